# Optimizing a Trainium2 kernel written in Bass

```python
import math
import jax, jax.numpy as jnp
from jax import lax
import numpy as np

D_MODEL = 1024
BATCH = 8
SEQ = 4096
DEPTH = 2

GRID_W = 64
CTX_LEN = 256
N_MIXERS = 2
N_RWKV = (DEPTH + 1) // 2
N_ATTN = DEPTH // 2
RWKV_HEAD = 64
RWKV_HEADS = D_MODEL // RWKV_HEAD
DECAY_LORA = 64
AAA_LORA = 64
GATE_LORA = 128
GN_EPS = 64e-5
ATTN_HEAD = 64
ATTN_HEADS = D_MODEL // ATTN_HEAD
KV_HEADS = 4
KV_GROUP = ATTN_HEADS // KV_HEADS
Q_BLOCK = 128
ROPE_THETA = 10000.0
D_FF = 4 * D_MODEL
NORM_EPS = 1e-6

kernel_name = "hybrid_rwkv7_gqa_dit_block"


def rms_norm(x, gain, eps=NORM_EPS):
    xf = x.astype(jnp.float32)
    y = xf * lax.rsqrt(jnp.mean(xf * xf, -1, keepdims=True) + eps)
    return (y * gain.astype(jnp.float32)).astype(x.dtype)


def modulate(h, shift, scale):
    return h * (1 + scale) + shift


def sqrelu_mlp(h, w1, w2):
    return jnp.square(jax.nn.relu(h @ w1)) @ w2


def grid_shift(x, rows):
    b, s, d = x.shape
    q = d // 4
    g = x.reshape(b, rows, GRID_W, d)
    left = jnp.pad(g[:, :, :-1, :q], ((0, 0), (0, 0), (1, 0), (0, 0)))
    right = jnp.pad(g[:, :, 1:, q:2 * q], ((0, 0), (0, 0), (0, 1), (0, 0)))
    up = jnp.pad(g[:, :-1, :, 2 * q:3 * q], ((0, 0), (1, 0), (0, 0), (0, 0)))
    down = jnp.pad(g[:, 1:, :, 3 * q:], ((0, 0), (0, 1), (0, 0), (0, 0)))
    return jnp.concatenate([left, right, up, down], -1).reshape(b, s, d)


def seq_shift(x):
    h = x.shape[-1] // 2
    prev = jnp.pad(x[:, :-1, :h], ((0, 0), (1, 0), (0, 0)))
    nxt = jnp.pad(x[:, 1:, h:], ((0, 0), (0, 1), (0, 0)))
    return jnp.concatenate([prev, nxt], -1)


def _wkv_step(state, inp):
    r, w, k, v, kk, bvec = inp
    sa = jnp.einsum('bhvk,bhk->bhv', state, -kk)
    state = state * w[:, :, None, :] + sa[..., None] * bvec[:, :, None, :] + v[..., None] * k[:, :, None, :]
    y = jnp.einsum('bhvk,bhk->bhv', state, r)
    return state, y


def wkv_scan(r, w, k, v, kk, bvec, state0, reverse):
    seqs = tuple(jnp.swapaxes(t, 0, 1) for t in (r, w, k, v, kk, bvec))
    state, y = lax.scan(_wkv_step, state0, seqs, reverse=reverse)
    return state, jnp.swapaxes(y, 0, 1)


def _rwkv_project(h, hs, mu, wr, wk, wv, w0, w1, w2, a0, a1, a2, g1, g2, k_k, k_a):
    b, t, d = h.shape
    hf = h.astype(jnp.float32)
    xx = hs.astype(jnp.float32) - hf
    xr, xw, xk, xv, xa, xg = [hf + xx * mu[j] for j in range(6)]
    split = lambda z: z.reshape(b, t, RWKV_HEADS, RWKV_HEAD)
    r = xr @ wr
    k = xk @ wk
    v = xv @ wv
    g = jax.nn.sigmoid(xg @ g1) @ g2
    kk = split(k * k_k)
    kk = kk * lax.rsqrt(jnp.sum(kk * kk, -1, keepdims=True) + 1e-12)
    dirs = []
    for dr in range(2):
        w_log = -jax.nn.softplus(-(w0[dr] + jnp.tanh(xw @ w1[dr]) @ w2[dr])) - 0.5
        decay = jnp.exp(-jnp.exp(w_log))
        a = jax.nn.sigmoid(a0[dr] + (xa @ a1[dr]) @ a2[dr])
        k_dir = k * (1 + (a - 1) * k_a)
        dirs.append((split(decay), split(k_dir), split(a)))
    return split(r), split(v), g, kk, dirs


def _rwkv_scans(p, init_f, init_b):
    r, v, g, kk, dirs = p
    (dec_f, k_f, a_f), (dec_b, k_b, a_b) = dirs
    s_f, y_f = wkv_scan(r, dec_f, k_f, v, kk, kk * a_f, init_f, False)
    s_b, y_b = wkv_scan(r, dec_b, k_b, v, kk, kk * a_b, init_b, True)
    return s_f, s_b, y_f + y_b


def _rwkv_readout(y, p, r_k, ln_w, ln_b, wo, out_dtype):
    r, v, g, kk, dirs = p
    b, t = y.shape[:2]
    mean = jnp.mean(y, -1, keepdims=True)
    var = jnp.mean(jnp.square(y - mean), -1, keepdims=True)
    yn = ((y - mean) * lax.rsqrt(var + GN_EPS)).reshape(b, t, D_MODEL) * ln_w + ln_b
    bonus = (jnp.sum(r * dirs[0][1] * r_k, -1, keepdims=True)
             + jnp.sum(r * dirs[1][1] * r_k, -1, keepdims=True)) * v
    out = (yn + bonus.reshape(b, t, D_MODEL)) * g
    return (out @ wo).astype(out_dtype)


def rwkv_mix(h_lat, h_ctx, rows, need_ctx, mu, wr, wk, wv, wo, w0, w1, w2, a0, a1, a2, g1, g2, k_k, k_a, r_k, ln_w, ln_b):
    pw = (mu, wr, wk, wv, w0, w1, w2, a0, a1, a2, g1, g2, k_k, k_a)
    zero = jnp.zeros((h_lat.shape[0], RWKV_HEADS, RWKV_HEAD, RWKV_HEAD), jnp.float32)
    p_ctx = _rwkv_project(h_ctx, seq_shift(h_ctx), *pw)
    s_f, s_b, y_ctx = _rwkv_scans(p_ctx, zero, zero)
    p_lat = _rwkv_project(h_lat, grid_shift(h_lat, rows), *pw)
    _, _, y_lat = _rwkv_scans(p_lat, s_f, s_b)
    o_lat = _rwkv_readout(y_lat, p_lat, r_k, ln_w, ln_b, wo, h_lat.dtype)
    o_ctx = _rwkv_readout(y_ctx, p_ctx, r_k, ln_w, ln_b, wo, h_ctx.dtype) if need_ctx else None
    return o_lat, o_ctx


def rope_2d_tables(n_tokens):
    t = jnp.arange(n_tokens)
    row = (t // GRID_W).astype(jnp.float32)
    col = (t % GRID_W).astype(jnp.float32)
    half = ATTN_HEAD // 2
    freqs = ROPE_THETA ** (-jnp.arange(0, half, 2, dtype=jnp.float32) / half)
    ang = jnp.stack([row[:, None] * freqs, col[:, None] * freqs], 0)
    return jnp.cos(ang), jnp.sin(ang)


def apply_rope_2d(x, cos, sin):
    half = ATTN_HEAD // 2
    quarter = half // 2
    outs = []
    for ax in range(2):
        xa = x[..., ax * half:(ax + 1) * half]
        x1, x2 = xa[..., :quarter], xa[..., quarter:]
        c = cos[ax][None, :, None, :]
        s = sin[ax][None, :, None, :]
        outs += [x1 * c - x2 * s, x2 * c + x1 * s]
    return jnp.concatenate(outs, -1)


def attn_mix(h_lat, h_ctx, cos, sin, need_ctx, wqkv, q_norm, k_norm, wo):
    nq = ATTN_HEADS * ATTN_HEAD
    nk = KV_HEADS * ATTN_HEAD
    scale = ATTN_HEAD ** -0.5

    def project(h):
        b, t, _ = h.shape
        qkv = h @ wqkv
        q = qkv[..., :nq].reshape(b, t, ATTN_HEADS, ATTN_HEAD)
        k = qkv[..., nq:nq + nk].reshape(b, t, KV_HEADS, ATTN_HEAD)
        v = qkv[..., nq + nk:].reshape(b, t, KV_HEADS, ATTN_HEAD)
        q = rms_norm(q, q_norm).astype(jnp.float32)
        k = rms_norm(k, k_norm).astype(jnp.float32)
        return q, k, v.astype(jnp.float32)

    ql, kl, vl = project(h_lat)
    ql = apply_rope_2d(ql, cos, sin)
    kl = apply_rope_2d(kl, cos, sin)
    qc, kc, vc = project(h_ctx)
    k_all = jnp.concatenate([kl, kc], 1)
    v_all = jnp.concatenate([vl, vc], 1)
    b, s = ql.shape[:2]
    nb = s // Q_BLOCK
    qb = ql.reshape(b, nb, Q_BLOCK, KV_HEADS, KV_GROUP, ATTN_HEAD).transpose(1, 0, 2, 3, 4, 5)

    def block(q):
        sc = jnp.einsum('bqkgd,bskd->bkgqs', q, k_all) * scale
        p = jax.nn.softmax(sc, -1)
        return jnp.einsum('bkgqs,bskd->bqkgd', p, v_all)

    o = lax.map(block, qb)
    o_lat = (o.transpose(1, 0, 2, 3, 4, 5).reshape(b, s, nq) @ wo).astype(h_lat.dtype)
    o_ctx = None
    if need_ctx:
        c_len = qc.shape[1]
        qg = qc.reshape(b, c_len, KV_HEADS, KV_GROUP, ATTN_HEAD)
        p = jax.nn.softmax(jnp.einsum('bqkgd,bskd->bkgqs', qg, kc) * scale, -1)
        oc = jnp.einsum('bkgqs,bskd->bqkgd', p, vc).reshape(b, c_len, nq)
        o_ctx = (oc @ wo).astype(h_ctx.dtype)
    return o_lat, o_ctx


def setup_inputs(seed: int = 0) -> dict:
    key = jax.random.key(seed)
    ks = iter(jax.random.split(key, 48))
    D = D_MODEL
    nrm = lambda shape, std: jax.random.normal(next(ks), shape, jnp.float32) * std
    uni = lambda shape, lo, hi: jax.random.uniform(next(ks), shape, jnp.float32, lo, hi)
    qkv_w = (ATTN_HEADS + 2 * KV_HEADS) * ATTN_HEAD
    return {
        "x": nrm((BATCH, SEQ, D), 1.0),
        "c": nrm((BATCH, D), 1.0),
        "ctx": nrm((BATCH, CTX_LEN, D), 1.0),
        "c_ctx": nrm((D,), 1.0),
        "w_mod": nrm((DEPTH, D, 6 * D), 0.5 * D ** -0.5),
        "b_mod": nrm((DEPTH, 6 * D), 0.01),
        "norm_mix": 1.0 + nrm((DEPTH, D), 0.05),
        "norm_mlp": 1.0 + nrm((DEPTH, D), 0.05),
        "mlp_w1": nrm((DEPTH, D, D_FF), D ** -0.5),
        "mlp_w2": nrm((DEPTH, D_FF, D), D_FF ** -0.5),
        "rwkv_mu": uni((N_RWKV, 6, D), 0.0, 1.0),
        "rwkv_wr": nrm((N_RWKV, D, D), D ** -0.5),
        "rwkv_wk": nrm((N_RWKV, D, D), D ** -0.5),
        "rwkv_wv": nrm((N_RWKV, D, D), D ** -0.5),
        "rwkv_wo": nrm((N_RWKV, D, D), D ** -0.5),
        "rwkv_w0": uni((N_RWKV, 2, D), -4.0, 1.0),
        "rwkv_w1": nrm((N_RWKV, 2, D, DECAY_LORA), D ** -0.5),
        "rwkv_w2": nrm((N_RWKV, 2, DECAY_LORA, D), 0.5 * DECAY_LORA ** -0.5),
        "rwkv_a0": nrm((N_RWKV, 2, D), 0.1),
        "rwkv_a1": nrm((N_RWKV, 2, D, AAA_LORA), D ** -0.5),
        "rwkv_a2": nrm((N_RWKV, 2, AAA_LORA, D), 0.5 * AAA_LORA ** -0.5),
        "rwkv_g1": nrm((N_RWKV, D, GATE_LORA), D ** -0.5),
        "rwkv_g2": nrm((N_RWKV, GATE_LORA, D), GATE_LORA ** -0.5),
        "rwkv_k_k": 0.85 + nrm((N_RWKV, D), 0.05),
        "rwkv_k_a": 1.0 + nrm((N_RWKV, D), 0.05),
        "rwkv_r_k": nrm((N_RWKV, RWKV_HEADS, RWKV_HEAD), 0.1),
        "rwkv_ln_w": 1.0 + nrm((N_RWKV, D), 0.05),
        "rwkv_ln_b": nrm((N_RWKV, D), 0.01),
        "attn_wqkv": nrm((N_ATTN, D, qkv_w), D ** -0.5),
        "attn_q_norm": 1.0 + nrm((N_ATTN, ATTN_HEAD), 0.05),
        "attn_k_norm": 1.0 + nrm((N_ATTN, ATTN_HEAD), 0.05),
        "attn_wo": nrm((N_ATTN, ATTN_HEADS * ATTN_HEAD, D), (ATTN_HEADS * ATTN_HEAD) ** -0.5),
        "final_norm": 1.0 + nrm((D,), 0.05),
    }


def reference(x, c, ctx, c_ctx, w_mod, b_mod, norm_mix, norm_mlp, mlp_w1, mlp_w2,
              rwkv_mu, rwkv_wr, rwkv_wk, rwkv_wv, rwkv_wo, rwkv_w0, rwkv_w1, rwkv_w2,
              rwkv_a0, rwkv_a1, rwkv_a2, rwkv_g1, rwkv_g2, rwkv_k_k, rwkv_k_a, rwkv_r_k,
              rwkv_ln_w, rwkv_ln_b, attn_wqkv, attn_q_norm, attn_k_norm, attn_wo, final_norm):
    s = x.shape[1]
    rows = s // GRID_W
    cos, sin = rope_2d_tables(s)
    sc = jax.nn.silu(c)
    scc = jax.nn.silu(c_ctx)
    xl, xc = x, ctx
    for i in range(DEPTH):
        j = i // N_MIXERS
        need_ctx = i < DEPTH - 1
        m_l = jnp.split(sc @ w_mod[i] + b_mod[i], 6, -1)
        m_c = jnp.split(scc @ w_mod[i] + b_mod[i], 6, -1)
        h_l = modulate(rms_norm(xl, norm_mix[i]), m_l[0][:, None], m_l[1][:, None])
        h_c = modulate(rms_norm(xc, norm_mix[i]), m_c[0], m_c[1])
        if i % N_MIXERS == 0:
            o_l, o_c = rwkv_mix(h_l, h_c, rows, need_ctx, rwkv_mu[j], rwkv_wr[j], rwkv_wk[j], rwkv_wv[j], rwkv_wo[j],
                                rwkv_w0[j], rwkv_w1[j], rwkv_w2[j], rwkv_a0[j], rwkv_a1[j], rwkv_a2[j],
                                rwkv_g1[j], rwkv_g2[j], rwkv_k_k[j], rwkv_k_a[j], rwkv_r_k[j],
                                rwkv_ln_w[j], rwkv_ln_b[j])
        else:
            o_l, o_c = attn_mix(h_l, h_c, cos, sin, need_ctx, attn_wqkv[j], attn_q_norm[j],
                                attn_k_norm[j], attn_wo[j])
        xl = xl + m_l[2][:, None] * o_l
        h_l = modulate(rms_norm(xl, norm_mlp[i]), m_l[3][:, None], m_l[4][:, None])
        xl = xl + m_l[5][:, None] * sqrelu_mlp(h_l, mlp_w1[i], mlp_w2[i])
        if need_ctx:
            xc = xc + m_c[2] * o_c
            h_c = modulate(rms_norm(xc, norm_mlp[i]), m_c[3], m_c[4])
            xc = xc + m_c[5] * sqrelu_mlp(h_c, mlp_w1[i], mlp_w2[i])
    return rms_norm(xl, final_norm).astype(x.dtype)
```

```python
from contextlib import ExitStack, contextmanager
import numpy as np
import concourse.bass as bass
import concourse.mybir as mybir
from concourse.bass_utils import run_bass_kernel_spmd

F32 = mybir.dt.float32
BF16 = mybir.dt.bfloat16
AF = mybir.ActivationFunctionType
ALU = mybir.AluOpType
AX = mybir.AxisListType

D = 1024
T = 4096
C = 256
TT = T + C
NCORES = 8
C0 = float(np.exp(-0.5))
NV = 21
ENGS = ("tensor", "vector", "scalar", "gpsimd", "sync")


class Prog:
    def __init__(self, nc):
        self.nc = nc
        self.ges = ExitStack()
        self.sems = {}
        self.cnt = {}
        self.dpool = {False: [], True: []}
        self.seen = {e: {} for e in ENGS}
        self.n = 0
        self.pes = None
        self.total_ops = 0

    def _alloc(self, es, fn, shape, dtype, name):
        self.n += 1
        return es.enter_context(fn(name or f"t{self.n}", list(shape), dtype))

    def gsb(self, shape, dtype, name=None):
        return self._alloc(self.ges, self.nc.sbuf_tensor, shape, dtype, name)

    def sb(self, shape, dtype, name=None):
        return self._alloc(self.pes, self.nc.sbuf_tensor, shape, dtype, name)

    @contextmanager
    def scope(self):
        self.ses = ExitStack()
        yield self
        self.ses.close()
        self.ses = None

    def ssb(self, shape, dtype, name=None):
        return self._alloc(self.ses, self.nc.sbuf_tensor, shape, dtype, name)

    def ps(self, shape, dtype, name=None):
        return self._alloc(self.pes, self.nc.psum_tensor, shape, dtype, name)

    @contextmanager
    def phase(self, name):
        self.ops = []
        self.last_w = {}
        self.readers = {}
        self.last_dma = {}
        self.pes = ExitStack()
        self.pname = name
        yield self
        self._emit()
        self.pes.close()
        self.pes = None

    def _deps(self, reads, writes):
        deps = {}
        for r in reads:
            if r in self.last_w:
                deps.setdefault(self.last_w[r], set()).add("RAW")
        for w in writes:
            if w in self.last_w:
                deps.setdefault(self.last_w[w], set()).add("WAW")
            for rd in self.readers.get(w, ()):
                deps.setdefault(rd, set()).add("WAR")
        idx = len(self.ops)
        for r in reads:
            self.readers.setdefault(r, []).append(idx)
        for w in writes:
            self.last_w[w] = idx
            self.readers[w] = []
        return deps

    def op(self, eng, fn, reads=(), writes=()):
        deps = self._deps(tuple(reads), tuple(writes))
        self.ops.append(dict(eng=eng, fn=fn, deps=deps, dma=None))
        return len(self.ops) - 1

    def dma(self, queue, out, in_, reads=(), writes=(), sem=None):
        deps = self._deps(tuple(reads), tuple(writes))
        prev = self.last_dma.get(sem)
        if prev is not None:
            deps.setdefault(prev, set()).add("SER")
        idx = len(self.ops)
        self.last_dma[sem] = idx
        self.ops.append(dict(eng=queue, fn=lambda e: e.dma_start(out=out, in_=in_), deps=deps, dma=sem))
        return idx

    def _emit(self):
        nc = self.nc
        ops = self.ops
        if self.last_dma:
            ops.append(dict(eng="sync", fn=None, deps={i: {"FIN"} for i in self.last_dma.values()}, dma=None))
        self.total_ops += len(ops)

        def needs_wait(x, d, kinds):
            if d["dma"] is not None or x["dma"] is not None:
                return True
            if d["eng"] != x["eng"]:
                return True
            if x["eng"] == "tensor":
                return False
            return bool(kinds & {"RAW", "FIN"})

        signal = [False] * len(ops)
        for x in ops:
            for di, kinds in x["deps"].items():
                d = ops[di]
                if d["dma"] is None and needs_wait(x, d, kinds):
                    signal[di] = True
        dkeys = {}
        nk = {False: 0, True: 0}
        for o in ops:
            if o["dma"] is not None and o["dma"] not in dkeys:
                sw = o["eng"] == "gpsimd"
                dkeys[o["dma"]] = (sw, nk[sw])
                nk[sw] += 1
        for sw in (False, True):
            while len(self.dpool[sw]) < nk[sw]:
                h = self.ges.enter_context(nc.semaphore(f"dq{int(sw)}_{len(self.dpool[sw])}"))
                self.dpool[sw].append([h, 0])
        for e in ENGS:
            if e not in self.sems:
                self.sems[e] = self.ges.enter_context(nc.semaphore(f"e_{e}"))
        token = [None] * len(ops)
        for i, o in enumerate(ops):
            if o["dma"] is not None:
                dk = dkeys[o["dma"]]
                slot = self.dpool[dk[0]][dk[1]]
                slot[1] += 16
                token[i] = (("d", dk), slot[1])
            elif signal[i]:
                self.cnt[o["eng"]] = self.cnt.get(o["eng"], 0) + 1
                token[i] = (("e", o["eng"]), self.cnt[o["eng"]])
        per_eng = {e: [] for e in ENGS}
        for i, o in enumerate(ops):
            per_eng[o["eng"]].append(i)

        def semh(key):
            return self.dpool[key[1][0]][key[1][1]][0] if key[0] == "d" else self.sems[key[1]]

        def run(engname, eng):
            seen = self.seen[engname]
            for i in per_eng[engname]:
                o = ops[i]
                waits = {}
                for di, kinds in o["deps"].items():
                    d = ops[di]
                    if not needs_wait(o, d, kinds):
                        continue
                    key, val = token[di]
                    if waits.get(key, 0) < val:
                        waits[key] = val
                for key, val in waits.items():
                    if seen.get(key, 0) >= val:
                        continue
                    seen[key] = val
                    eng.wait_ge(semh(key), val)
                if o["fn"] is None:
                    continue
                ins = o["fn"](eng)
                if o["dma"] is not None:
                    ins.then_inc(semh(token[i][0]), 16)
                elif signal[i]:
                    ins.then_inc(self.sems[engname], 1)

        with nc.Block() as block:
            @block.sync
            def _(e):
                run("sync", e)

            @block.tensor
            def _(e):
                run("tensor", e)

            @block.vector
            def _(e):
                run("vector", e)

            @block.scalar
            def _(e):
                run("scalar", e)

            @block.gpsimd
            def _(e):
                run("gpsimd", e)

    def close(self):
        self.ges.close()

    def mm(self, out, lhsT, rhs, start, stop, r, w):
        self.op("tensor", lambda e: e.matmul(out, lhsT=lhsT, rhs=rhs, start=start, stop=stop), r, w)

    def tr(self, out, in_, ident, r, w):
        self.op("tensor", lambda e: e.transpose(out, in_, ident), r, w)

    def tt(self, eng, out, in0, in1, op, r, w):
        self.op(eng, lambda e: e.tensor_tensor(out=out, in0=in0, in1=in1, op=op), r, w)

    def ts(self, eng, out, in0, s1, s2, op0, op1, r, w):
        if op1 is None:
            self.op(eng, lambda e: e.tensor_scalar(out=out, in0=in0, scalar1=s1, scalar2=None, op0=op0), r, w)
        else:
            self.op(eng, lambda e: e.tensor_scalar(out=out, in0=in0, scalar1=s1, scalar2=s2, op0=op0, op1=op1), r, w)

    def stt(self, out, in0, scalar, in1, op0, op1, r, w):
        self.op("vector", lambda e: e.scalar_tensor_tensor(out=out, in0=in0, scalar=scalar, in1=in1, op0=op0, op1=op1), r, w)

    def act(self, out, in_, func, r, w, bias=None, scale=None):
        kw = {}
        if bias is not None:
            kw["bias"] = bias
        if scale is not None:
            kw["scale"] = scale
        self.op("scalar", lambda e: e.activation(out=out, in_=in_, func=func, **kw), r, w)

    def cp(self, eng, out, in_, r, w):
        if eng == "scalar":
            self.op(eng, lambda e: e.activation(out=out, in_=in_, func=AF.Copy), r, w)
        else:
            self.op(eng, lambda e: e.tensor_copy(out=out, in_=in_), r, w)

    def memset(self, eng, ap, val, w):
        self.op(eng, lambda e: e.memset(ap, val), (), w)


def fm(ap2d):
    return ap2d.rearrange("(c p) n -> p c n", p=128)


LAT_TILES = [(i * 512, 512, False) for i in range(8)]
ALL_TILES = LAT_TILES + [(T, 256, True)]


def stage_init(P, io, G):
    nc = P.nc
    G["identf"] = P.gsb([128, 128], F32, "identf")
    G["identb"] = P.gsb([128, 128], BF16, "identb")
    G["onesb"] = P.gsb([128, 128], BF16, "onesb")
    G["bones"] = P.gsb([128, 128], BF16, "bones")
    G["masks"] = P.gsb([128, 4, 128], BF16, "masks")
    G["perm"] = P.gsb([128, 128], BF16, "perm")
    G["rmask"] = P.gsb([128, 256], F32, "rmask")
    G["vec"] = P.gsb([128, NV, 8], F32, "vec")
    G["modv"] = P.gsb([128, 2, 6, 8, 2], F32, "modv")
    G["gg"] = P.gsb([128, 2, 2, 8, 2], F32, "gg")
    with P.phase("init"):
        P.dma("sync", G["identf"][:], io["c_ident"], writes=["identf"], sem="identf")
        P.dma("sync", G["rmask"][:], io["c_rmask"], writes=["rmask"], sem="rmask")
        P.dma("sync", G["vec"][:], io["vecs"], writes=["vec"], sem="vec")
        P.dma("gpsimd", G["identb"][:], io["c_ident"], writes=["identb"], sem="identb")
        P.dma("gpsimd", G["onesb"][:], io["c_ones"], writes=["onesb"], sem="onesb")
        P.dma("gpsimd", G["bones"][:], io["c_bones"], writes=["bones"], sem="bones")
        P.dma("gpsimd", G["masks"][:], io["c_masks"], writes=["masks"], sem="masks")
        P.dma("gpsimd", G["perm"][:], io["c_perm"], writes=["perm"], sem="perm")
        vec = G["vec"]
        P.ts("vector", vec[:, 17, :], vec[:, 16, :], -1.0, 1.0, ALU.mult, ALU.add, ["vec"], ["vec"])
        sv = P.sb([128, 8, 2], F32)
        svs = P.sb([128, 8, 2], F32)
        P.dma("sync", sv[:], io["cvec"], writes=["sv"], sem="sv")
        P.act(svs[:], sv[:], AF.Silu, ["sv"], ["svs"])
        brow = P.sb([2, 2 * 6144], F32)
        row = P.sb([2, 2 * 6144], F32)
        P.dma("sync", brow[:], io["b_mod"].rearrange("l n -> (l n)").partition_broadcast(2), writes=["brow"], sem="brow")
        wt = [P.sb([128, 8, 512], F32) for _ in range(2)]
        psr = [P.ps([128, 512], F32) for _ in range(2)]
        pst = P.ps([128, 512], F32)
        k = 0
        for l in range(2):
            for nb in range(12):
                b = k % 2
                k += 1
                P.dma("sync", wt[b][:], fm(io["w_mod"][l, :, nb * 512:(nb + 1) * 512]), writes=[f"wt{b}"], sem=f"wt{b}")
                for c in range(8):
                    P.mm(psr[b][0:2, :], svs[:, c, :], wt[b][:, c, :], c == 0, c == 7, ["svs", f"wt{b}"], [f"psr{b}"])
                o = l * 6144 + nb * 512
                P.tt("vector", row[:, o:o + 512], psr[b][0:2, :], brow[:, o:o + 512], ALU.add, [f"psr{b}", "brow"], ["row"])
        for l in range(2):
            for blk in range(48):
                o = l * 6144 + blk * 128
                P.tr(pst[:, l * 96 + blk * 2:l * 96 + blk * 2 + 2], row[0:2, o:o + 128], G["identf"][0:2, 0:2], ["row", "identf"], ["pst"])
        P.cp("vector", G["modv"][:].rearrange("p l m c j -> p (l m c j)"), pst[:, 0:192], ["pst"], ["modv"])
        modv, gg = G["modv"], G["gg"]
        for l in range(2):
            for kind in range(2):
                sc = modv[:, l, 1 + 3 * kind, :, :]
                nv = vec[:, (0 if kind == 0 else 2) + l, :].unsqueeze(2).broadcast_to([128, 8, 2])
                P.ts("vector", gg[:, l, kind, :, :], sc, 1.0, None, ALU.add, None, ["modv"], ["gg"])
                P.tt("vector", gg[:, l, kind, :, :], gg[:, l, kind, :, :], nv, ALU.mult, ["gg", "vec"], ["gg"])


def mod_scalars(G, l, kind, isctx):
    j = 1 if isctx else 0
    gains = [G["gg"][:, l, kind, c, j:j + 1] for c in range(8)]
    shifts = [G["modv"][:, l, 3 * kind, c, j:j + 1] for c in range(8)]
    gates = [G["modv"][:, l, 3 * kind + 2, c, j:j + 1] for c in range(8)]
    return gains, shifts, gates


def stage_norm(P, io, G, name, src, tiles, gains_fn, shifts_fn, dst_fn, out_dtype):
    with P.phase(name):
        xt = [P.sb([128, 8, 512], F32) for _ in range(2)]
        sq = P.sb([128, 8, 512], BF16)
        lnv = P.sb([128, 512], F32)
        rstd = P.sb([128, 512], F32)
        tmp = [P.sb([128, 512], F32) for _ in range(2)]
        ho = [P.sb([128, 8, 512], out_dtype) for _ in range(2)]
        ps = [P.ps([128, 512], F32) for _ in range(2)]

        def load(i):
            c0, tw, _ = tiles[i]
            b = i % 2
            P.dma("sync", xt[b][:, :, :tw], fm(src[:, c0:c0 + tw]), writes=[f"xt{b}"], sem=f"xt{b}")

        load(0)
        for i, (c0, tw, isctx) in enumerate(tiles):
            b = i % 2
            if i + 1 < len(tiles):
                load(i + 1)
            gains = gains_fn(isctx)
            shifts = shifts_fn(isctx)
            P.act(sq[:, :, :tw], xt[b][:, :, :tw], AF.Square, [f"xt{b}"], ["sq"])
            for c in range(8):
                P.mm(ps[b][:, :tw], G["onesb"][:], sq[:, c, :tw], c == 0, c == 7, ["sq", "onesb"], [f"ps{b}"])
            P.act(lnv[:, :tw], ps[b][:, :tw], AF.Ln, [f"ps{b}"], ["lnv"], bias=1e-6, scale=1.0 / D)
            P.act(rstd[:, :tw], lnv[:, :tw], AF.Exp, ["lnv"], ["rstd"], scale=-0.5)
            for c in range(8):
                if shifts is None:
                    P.stt(ho[b][:, c, :tw], xt[b][:, c, :tw], gains[c], rstd[:, :tw], ALU.mult, ALU.mult,
                          [f"xt{b}", "rstd", "vec", "gg"], [f"ho{b}"])
                else:
                    t = tmp[c % 2]
                    P.stt(t[:, :tw], xt[b][:, c, :tw], gains[c], rstd[:, :tw], ALU.mult, ALU.mult,
                          [f"xt{b}", "rstd", "vec", "gg"], [f"tmp{c % 2}"])
                    P.act(ho[b][:, c, :tw], t[:, :tw], AF.Identity, [f"tmp{c % 2}", "modv"], [f"ho{b}"], bias=shifts[c])
            P.dma("sync", dst_fn(c0, tw, isctx), ho[b][:, :, :tw], reads=[f"ho{b}"], writes=[("dst", i)], sem=f"ho{b}")


def stage_mlp(P, io, G, l, tiles, xa, hb):
    for half in range(2):
        with P.phase(f"mlp{l}{half}"):
            w1 = P.sb([128, 8, 2048], BF16)
            w2 = P.sb([128, 16, 1024], BF16)
            for q in range(2):
                P.dma("gpsimd", w1[:, :, q * 1024:(q + 1) * 1024],
                      fm(io["mlp_w1"][l, :, half * 2048 + q * 1024: half * 2048 + (q + 1) * 1024]), writes=["w1"], sem=f"w1{q}")
                P.dma("gpsimd", w2[:, q * 8:(q + 1) * 8, :],
                      io["mlp_w2"][l, half * 2048 + q * 1024: half * 2048 + (q + 1) * 1024, :].rearrange("(f p) n -> p f n", p=128),
                      writes=["w2"], sem=f"w2{q}")
            xt = [P.sb([128, 8, 512], F32) for _ in range(2)]
            ht = [P.sb([128, 8, 512], BF16) for _ in range(2)]
            h1 = P.sb([128, 16, 512], BF16)
            r1 = [P.sb([128, 512], F32) for _ in range(2)]
            ps = [P.ps([128, 512], F32) for _ in range(4)]

            def load(i):
                c0, tw, _ = tiles[i]
                b = i % 2
                P.dma("sync", ht[b][:, :, :tw], fm(hb[:, c0:c0 + tw]), writes=[f"ht{b}"], sem=f"ht{b}")
                P.dma("sync", xt[b][:, :, :tw], fm(xa[:, c0:c0 + tw]), reads=[("xa", i)], writes=[f"xt{b}"], sem=f"xt{b}")

            load(0)
            for i, (c0, tw, isctx) in enumerate(tiles):
                b = i % 2
                if i + 1 < len(tiles):
                    load(i + 1)
                _, _, gates = mod_scalars(G, l, 1, isctx)
                for fc in range(16):
                    pb = fc % 2
                    for c in range(8):
                        P.mm(ps[pb][:, :tw], w1[:, c, fc * 128:(fc + 1) * 128], ht[b][:, c, :tw], c == 0, c == 7,
                             ["w1", f"ht{b}"], [f"ps{pb}"])
                    P.act(r1[pb][:, :tw], ps[pb][:, :tw], AF.Relu, [f"ps{pb}"], [f"r1{pb}"])
                    P.tt("gpsimd", h1[:, fc, :tw], r1[pb][:, :tw], r1[pb][:, :tw], ALU.mult, [f"r1{pb}"], [("h1", fc)])
                for oc in range(8):
                    pb = 2 + oc % 2
                    for fc in range(16):
                        P.mm(ps[pb][:, :tw], w2[:, fc, oc * 128:(oc + 1) * 128], h1[:, fc, :tw], fc == 0, fc == 15,
                             ["w2", ("h1", fc)], [f"ps{pb}"])
                    P.stt(xt[b][:, oc, :tw], ps[pb][:, :tw], gates[oc], xt[b][:, oc, :tw], ALU.mult, ALU.add,
                          [f"ps{pb}", f"xt{b}", "modv"], [f"xt{b}"])
                P.dma("sync", fm(xa[:, c0:c0 + tw]), xt[b][:, :, :tw], reads=[f"xt{b}"], writes=[("xa", i)], sem=f"xt{b}")


RW_ORDER1 = [(True, 0)] + [(False, i) for i in range(16)]
RW_ORDER2 = [(True, 0)] + [(False, i) for i in range(15, -1, -1)]


def rw_scratch(io):
    S = {}
    S["yp"] = io.scratch("rw_yp", [17, 8, 128, 256], F32)
    S["sadd"] = io.scratch("rw_sadd", [17, 8, 128, 256], F32)
    S["vst"] = io.scratch("rw_vst", [17, 8, 128, 256], F32)
    S["gst"] = io.scratch("rw_gst", [17, 8, 128, 256], F32)
    S["gyb"] = io.scratch("rw_gyb", [17, 8, 128, 512], BF16)
    S["gsb"] = io.scratch("rw_gsb", [17, 8, 128, 512], BF16)
    S["gamb"] = io.scratch("rw_gamb", [17, 128, 32], F32)
    S["bon"] = io.scratch("rw_bon", [17, 128, 32], F32)
    return S


def stage_rwkv1(P, io, G, hp, S, dbg=None):
    vec, masks, identb, identf, bones, onesb, rmask = (G[k] for k in ("vec", "masks", "identb", "identf", "bones", "onesb", "rmask"))
    with P.phase("rwkv1"):
        wr = P.sb([128, 8, 1024], BF16)
        wk = P.sb([128, 8, 1024], BF16)
        wv = P.sb([128, 8, 1024], BF16)
        for w, nm in ((wr, "rwkv_wr"), (wk, "rwkv_wk"), (wv, "rwkv_wv")):
            P.dma("gpsimd", w[:], fm(io[nm]), writes=[nm], sem=nm)
        lw1 = P.sb([128, 8, 128], BF16)
        la1 = P.sb([128, 8, 128], BF16)
        g1 = P.sb([128, 8, 128], BF16)
        for d in range(2):
            P.dma("gpsimd", lw1[:, :, d * 64:(d + 1) * 64], io["rwkv_w1"][d].rearrange("(c p) j -> p c j", p=128), writes=["lw1"], sem=f"lw1{d}")
            P.dma("gpsimd", la1[:, :, d * 64:(d + 1) * 64], io["rwkv_a1"][d].rearrange("(c p) j -> p c j", p=128), writes=["la1"], sem=f"la1{d}")
        P.dma("gpsimd", g1[:], io["rwkv_g1"].rearrange("(c p) j -> p c j", p=128), writes=["g1"], sem="g1")
        w2s = P.sb([128, 1024], BF16)
        a2s = P.sb([128, 1024], BF16)
        g2 = P.sb([128, 1024], BF16)
        P.dma("gpsimd", w2s[:], io["rwkv_w2"].rearrange("d j f -> (d j) f"), writes=["w2s"], sem="w2s")
        P.dma("gpsimd", a2s[:], io["rwkv_a2"].rearrange("d j f -> (d j) f"), writes=["a2s"], sem="a2s")
        P.dma("gpsimd", g2[:], io["rwkv_g2"], writes=["g2"], sem="g2")

        hh = P.sb([128, 8, 384], F32)
        xx = P.sb([128, 8, 256], F32)
        xr = P.sb([128, 8, 256], BF16)
        xk = P.sb([128, 8, 256], BF16)
        xv = P.sb([128, 8, 256], BF16)
        xrot = P.sb([128, 8, 256], BF16)
        lwt = P.sb([128, 256], BF16)
        lat = P.sb([128, 256], BF16)
        sg = P.sb([128, 256], BF16)
        f32t = {}
        for nm in ("r", "k", "sw0", "sw1", "ag0", "ag1", "kq", "lnv", "rs", "kkn", "fac", "kd0", "kd1", "b0", "b1",
                   "L", "Lx", "Lb", "E1", "E2", "E3", "ks"):
            f32t[nm] = P.sb([128, 256], F32, "t_" + nm)
        sqb = P.sb([128, 256], BF16)
        RK = P.sb([128, 4, 2, 64], BF16)
        VTbd = P.sb([128, 4, 128], F32)
        GTbd = P.sb([128, 4, 128], F32)
        Vf = P.sb([128, 4, 64], F32)
        Gf = P.sb([128, 4, 64], F32)
        YPs = P.sb([128, 4, 64], F32)
        SAs = P.sb([128, 4, 64], F32)
        gamb_t = P.sb([128, 8, 4], F32)
        bon_t = P.sb([128, 8, 4], F32)
        Sf = P.sb([128, 8, 64], BF16)
        ARq = [[P.sb([128, 4, 2, 128], BF16, f"AR{q}{d}") for d in range(2)] for q in range(2)]
        KTq = [[P.sb([128, 4, 128], BF16, f"KT{q}{d}") for d in range(2)] for q in range(2)]
        BTq = [[P.sb([128, 4, 128], BF16, f"BT{q}{d}") for d in range(2)] for q in range(2)]
        Vbq = [P.sb([128, 4, 64], BF16, f"Vb{q}") for q in range(2)]
        gamq = [[P.sb([128, 4], F32, f"gam{q}{d}") for d in range(2)] for q in range(2)]
        inv = []
        for d in range(2):
            st = {}
            for nm, shp in (("Atok", [128, 4, 128]), ("Btok", [128, 4, 128]), ("MQ", [128, 4, 256]), ("MWa", [128, 4, 2, 128]),
                            ("MWb", [128, 4, 2, 128]), ("MTa", [128, 4, 128]), ("MTb", [128, 4, 128])):
                st[nm] = P.sb(shp, BF16, f"i{d}_{nm}")
            inv.append(st)
        fin = []
        for q in range(2):
            row = []
            for d in range(2):
                st = {}
                for nm, shp in (("Ktok", [128, 4, 128]), ("NP", [128, 4, 256]), ("XW", [128, 4, 256]), ("NVb", [128, 4, 64]),
                                ("GY", [128, 4, 128]), ("GS", [128, 4, 128])):
                    st[nm] = P.sb(shp, BF16, f"f{q}{d}_{nm}")
                row.append(st)
            fin.append(row)
        ppt = P.ps([128, 512], F32)
        pp = [ppt[:, 0:256], ppt[:, 256:512]]
        pf = P.ps([128, 512], F32)
        pa = [P.ps([128, 512], F32) for _ in range(4)]
        pb = [P.ps([128, 512], F32) for _ in range(2)]
        cnt = {"pp": 0, "pa": 0, "pb": 0}
        nmod = {"pp": 2, "pa": 4, "pb": 2}

        def nxt(kind):
            i = cnt[kind] % nmod[kind]
            cnt[kind] += 1
            return i

        def a_step(mm_fn, evac_fn):
            for half in range(2):
                j = nxt("pa")
                for u in (2 * half, 2 * half + 1):
                    mm_fn(u, pa[j][:, (u % 2) * 256:(u % 2 + 1) * 256], f"pa{j}")
                evac_fn(slice(2 * half, 2 * half + 2), pa[j][:].rearrange("p (u a x) -> p u a x", a=2, x=128), f"pa{j}")

        for q in range(2):
            for d in range(2):
                P.memset("gpsimd", ARq[q][d][:], 0.0, [f"AR{q}{d}"])
                P.memset("gpsimd", KTq[q][d][:], 0.0, [f"KT{q}{d}"])
                P.memset("gpsimd", BTq[q][d][:], 0.0, [f"BT{q}{d}"])
        P.memset("gpsimd", RK[:], 0.0, ["RK"])
        P.memset("gpsimd", VTbd[:], 0.0, ["VTbd"])
        P.memset("gpsimd", GTbd[:], 0.0, ["GTbd"])
        P.memset("gpsimd", Sf[:], 0.0, [("Sf", p) for p in range(8)])

        def v3(ap):
            return ap.rearrange("p (u s) -> p u s", s=64)

        def u128(ap):
            return ap.rearrange("p (u x) -> p u x", x=128)

        def load_hh(ti):
            isctx, idx = RW_ORDER1[ti]
            off = 4288 if isctx else 64 + 256 * idx
            P.dma("sync", hh[:], fm(hp[:, off - 64: off + 320]), writes=["hh"], sem="hh")

        def proj8(w_cols_fn, xb, bn, extra_r):
            i = nxt("pp")
            for c in range(8):
                P.mm(pp[i], w_cols_fn(c), xb[:, c, :], c == 0, c == 7, [(bn, c)] + extra_r, ["ppb"])
            return i

        def tprep(ti):
            isctx, idx = RW_ORDER1[ti]
            hc = hh[:, :, 64:320]
            XXW = [("xx", c) for c in range(8)]
            if not isctx:
                h4 = hh[:, :, 64:320].rearrange("p c (r w) -> p c r w", w=64)
                x4 = xx[:].rearrange("p c (r w) -> p c r w", w=64)
                P.tt("vector", x4[:, 0:2, :, 1:64], h4[:, 0:2, :, 0:63], h4[:, 0:2, :, 1:64], ALU.subtract, ["hh"], XXW[0:2])
                P.ts("gpsimd", x4[:, 0:2, :, 0:1], h4[:, 0:2, :, 0:1], -1.0, None, ALU.mult, None, ["hh"], [("xxe", 0)])
                P.tt("vector", x4[:, 2:4, :, 0:63], h4[:, 2:4, :, 1:64], h4[:, 2:4, :, 0:63], ALU.subtract, ["hh"], XXW[2:4])
                P.ts("gpsimd", x4[:, 2:4, :, 63:64], h4[:, 2:4, :, 63:64], -1.0, None, ALU.mult, None, ["hh"], [("xxe", 1)])
                P.tt("gpsimd", xx[:, 4:6, :], hh[:, 4:6, 0:256], hh[:, 4:6, 64:320], ALU.subtract, ["hh"], XXW[4:6])
                P.tt("gpsimd", xx[:, 6:8, :], hh[:, 6:8, 128:384], hh[:, 6:8, 64:320], ALU.subtract, ["hh"], XXW[6:8])
            else:
                P.tt("vector", xx[:, 0:4, :], hh[:, 0:4, 63:319], hh[:, 0:4, 64:320], ALU.subtract, ["hh"], XXW[0:4] + [("xxe", 0)])
                P.tt("gpsimd", xx[:, 4:8, :], hh[:, 4:8, 65:321], hh[:, 4:8, 64:320], ALU.subtract, ["hh"], XXW[4:8] + [("xxe", 1)])
            yield

            def mk_xj(j, buf, bn):
                for c in range(8):
                    P.stt(buf[:, c, :], xx[:, c, :], vec[:, 5 + j, c:c + 1], hc[:, c, :], ALU.mult, ALU.add,
                          [("xx", c), ("xxe", 0), ("xxe", 1), "hh", "vec"], [(bn, c)])

            mk_xj(1, xrot, "xrot")
            yield
            i = proj8(lambda c: lw1[:, c, :], xrot, "xrot", ["lw1"])
            P.act(lwt[:], pp[i], AF.Tanh, ["ppb"], ["lwt"])
            yield
            mk_xj(4, xrot, "xrot")
            yield
            i = proj8(lambda c: la1[:, c, :], xrot, "xrot", ["la1"])
            P.cp("scalar", lat[:], pp[i], ["ppb"], ["lat"])
            yield
            mk_xj(5, xrot, "xrot")
            yield
            i = proj8(lambda c: g1[:, c, :], xrot, "xrot", ["g1"])
            P.act(sg[:], pp[i], AF.Sigmoid, ["ppb"], ["sg"])
            yield
            mk_xj(0, xr, "xr")
            yield
            mk_xj(2, xk, "xk")
            yield
            mk_xj(3, xv, "xv")
            if ti + 1 < len(RW_ORDER1):
                load_hh(ti + 1)
            yield

        def prep(ti, oc, q):
            isctx, idx = RW_ORDER1[ti]
            tg = 16 if isctx else idx
            cs = slice(oc * 128, (oc + 1) * 128)
            t = f32t
            AR, KT, BT, Vb, gam = ARq[q], KTq[q], BTq[q], Vbq[q], gamq[q]
            i = proj8(lambda c: wr[:, c, cs], xr, "xr", ["rwkv_wr"])
            P.cp("scalar", t["r"][:], pp[i], ["ppb"], ["r"])
            i = proj8(lambda c: wk[:, c, cs], xk, "xk", ["rwkv_wk"])
            P.cp("scalar", t["k"][:], pp[i], ["ppb"], ["k"])
            yield
            i = proj8(lambda c: wv[:, c, cs], xv, "xv", ["rwkv_wv"])
            vt4 = VTbd[:].rearrange("p u (h s) -> p u h s", h=2)
            for h2 in range(2):
                sl = slice(h2 * 64, (h2 + 1) * 64)
                P.cp("scalar", vt4[sl, :, h2, :], v3(pp[i][sl, :]), ["ppb"], ["VTbd"])
            i = nxt("pp")
            P.mm(pp[i], g2[:, cs], sg[:], True, True, ["g2", "sg"], ["ppb"])
            gt4 = GTbd[:].rearrange("p u (h s) -> p u h s", h=2)
            for h2 in range(2):
                sl = slice(h2 * 64, (h2 + 1) * 64)
                P.cp("scalar", gt4[sl, :, h2, :], v3(pp[i][sl, :]), ["ppb"], ["GTbd"])
            yield
            j = nxt("pb")
            for u in range(4):
                P.tr(pb[j][:, u * 128:(u + 1) * 128], VTbd[:, u, :], identf[:], ["VTbd", "identf"], [f"pb{j}"])
            pv = u128(pb[j][:])
            for h2 in range(2):
                sl = slice(h2 * 64, (h2 + 1) * 64)
                P.cp("scalar", Vf[sl, :, :], pv[sl, :, h2 * 64:(h2 + 1) * 64], [f"pb{j}"], ["Vf"])
            P.cp("gpsimd", Vb[:], Vf[:], ["Vf"], [f"Vb{q}"])
            P.dma("sync", S["vst"][tg, oc].rearrange("p (u s) -> p u s", s=64), Vf[:], reads=["Vf"], writes=[("vst", tg, oc)], sem="Vf")
            yield
            j = nxt("pb")
            for u in range(4):
                P.tr(pb[j][:, u * 128:(u + 1) * 128], GTbd[:, u, :], identf[:], ["GTbd", "identf"], [f"pb{j}"])
            pv = u128(pb[j][:])
            for h2 in range(2):
                sl = slice(h2 * 64, (h2 + 1) * 64)
                P.cp("scalar", Gf[sl, :, :], pv[sl, :, h2 * 64:(h2 + 1) * 64], [f"pb{j}"], ["Gf"])
            P.dma("sync", S["gst"][tg, oc].rearrange("p (u s) -> p u s", s=64), Gf[:], reads=["Gf"], writes=[("gst", tg, oc)], sem="Gf")
            yield
            for d in range(2):
                dl = slice(d * 64, (d + 1) * 64)
                i = nxt("pp")
                P.mm(pp[i], w2s[dl, cs], lwt[dl, :], True, True, ["w2s", "lwt"], ["ppb"])
                P.act(t[f"sw{d}"][:], pp[i], AF.Sigmoid, ["ppb", "vec"], [f"sw{d}"], bias=vec[:, 11 + d, oc:oc + 1])
                i = nxt("pp")
                P.mm(pp[i], a2s[dl, cs], lat[dl, :], True, True, ["a2s", "lat"], ["ppb"])
                P.act(t[f"ag{d}"][:], pp[i], AF.Sigmoid, ["ppb", "vec"], [f"ag{d}"], bias=vec[:, 13 + d, oc:oc + 1])
            yield
            P.ts("gpsimd", t["kq"][:], t["k"][:], vec[:, 15, oc:oc + 1], None, ALU.mult, None, ["k", "vec"], ["kq"])
            P.act(sqb[:], t["kq"][:], AF.Square, ["kq"], ["sqb"])
            i = nxt("pp")
            P.mm(pp[i], bones[:], sqb[:], True, True, ["bones", "sqb"], ["ppb"])
            P.act(t["lnv"][:], pp[i], AF.Ln, ["ppb"], ["lnv"], bias=1e-12)
            P.act(t["rs"][:], t["lnv"][:], AF.Exp, ["lnv"], ["rs"], scale=-0.5)
            P.tt("gpsimd", t["kkn"][:], t["kq"][:], t["rs"][:], ALU.mult, ["kq", "rs"], ["kkn"])
            yield
            for d in range(2):
                sw, ag, kd, bb = t[f"sw{d}"], t[f"ag{d}"], t[f"kd{d}"], t[f"b{d}"]
                P.ts("gpsimd", t["fac"][:], ag[:], vec[:, 16, oc:oc + 1], vec[:, 17, oc:oc + 1], ALU.mult, ALU.add, [f"ag{d}", "vec"], ["fac"])
                P.tt("gpsimd", kd[:], t["k"][:], t["fac"][:], ALU.mult, ["k", "fac"], [f"kd{d}"])
                P.tt("gpsimd", bb[:], t["kkn"][:], ag[:], ALU.mult, ["kkn", f"ag{d}"], [f"b{d}"])
                P.op("vector", lambda e, sw=sw: e.tensor_tensor_scan(out=t["L"][:], data0=rmask[:], data1=sw[:], initial=0.0,
                                                                      op0=ALU.mult, op1=ALU.add), [f"sw{d}", "rmask"], ["L"])
                L3 = v3(t["L"][:])
                if d == 0:
                    P.tt("gpsimd", t["Lx"][:], t["L"][:], sw[:], ALU.subtract, ["L", f"sw{d}"], ["Lx"])
                    Li, Lin = t["L"], "L"
                else:
                    P.tt("gpsimd", v3(t["Lx"][:]), L3[:, :, 63:64].broadcast_to([128, 4, 64]), L3, ALU.subtract, ["L"], ["Lx"])
                    P.tt("gpsimd", t["Lb"][:], t["Lx"][:], sw[:], ALU.add, ["Lx", f"sw{d}"], ["Lb"])
                    Li, Lin = t["Lb"], "Lb"
                yield
                P.act(t["E1"][:], Li[:], AF.Exp, [Lin], ["E1"], scale=-C0)
                P.act(t["E3"][:], Li[:], AF.Exp, [Lin], ["E3"], scale=C0)
                P.act(t["E2"][:], t["Lx"][:], AF.Exp, ["Lx"], ["E2"], scale=-C0)
                yield
                ar5 = AR[d][:].rearrange("p u a (h s) -> p u a h s", h=2)
                kt4 = KT[d][:].rearrange("p u (h s) -> p u h s", h=2)
                bt4 = BT[d][:].rearrange("p u (h s) -> p u h s", h=2)
                for h2 in range(2):
                    sl = slice(h2 * 64, (h2 + 1) * 64)
                    P.stt(ar5[sl, :, 0, h2, :], v3(t["kkn"][sl, :]), -1.0, v3(t["E2"][sl, :]), ALU.mult, ALU.mult, ["kkn", "E2"], [f"AR{q}{d}"])
                    P.tt("gpsimd", ar5[sl, :, 1, h2, :], v3(t["r"][sl, :]), v3(t["E1"][sl, :]), ALU.mult, ["r", "E1"], [f"AR{q}{d}"])
                    P.tt("vector", kt4[sl, :, h2, :], v3(kd[sl, :]), v3(t["E3"][sl, :]), ALU.mult, [f"kd{d}", "E3"], [f"KT{q}{d}"])
                    P.tt("gpsimd", bt4[sl, :, h2, :], v3(bb[sl, :]), v3(t["E3"][sl, :]), ALU.mult, [f"b{d}", "E3"], [f"BT{q}{d}"])
                E13 = v3(t["E1"][:])
                gsrc = E13[:, :, 63] if d == 0 else E13[:, :, 0]
                P.cp("vector", gam[d][:], gsrc, ["E1"], [f"gam{q}{d}"])
                if d == 1:
                    P.cp("gpsimd", gamb_t[:, oc, :], gam[1][:], [f"gam{q}1"], ["gamb_t"])
                yield
            P.tt("gpsimd", t["ks"][:], t["kd0"][:], t["kd1"][:], ALU.add, ["kd0", "kd1"], ["ks"])
            for h2 in range(2):
                sl = slice(h2 * 64, (h2 + 1) * 64)
                P.stt(RK[sl, :, h2, :], v3(t["r"][sl, :]), vec[sl, 18, oc:oc + 1], v3(t["ks"][sl, :]), ALU.mult, ALU.mult, ["r", "ks", "vec"], ["RK"])
            i = nxt("pp")
            for u in range(4):
                P.mm(pp[i][:, u:u + 1], RK[:, u, :, :].rearrange("p h s -> p (h s)"), onesb[:, 0:1], True, True, ["RK", "onesb"], ["ppb"])
            P.cp("scalar", bon_t[:, oc, :], pp[i][:, 0:4], ["ppb"], ["bon_t"])
            if oc == 7:
                P.dma("sync", S["gamb"][tg], gamb_t[:].rearrange("p a b -> p (a b)"), reads=["gamb_t"], writes=[("gamb", tg)], sem="gamb_t")
                P.dma("sync", S["bon"][tg], bon_t[:].rearrange("p a b -> p (a b)"), reads=["bon_t"], writes=[("bon", tg)], sem="bon_t")
            yield

        def chain(ti, oc, q, d):
            AR, KT, BT, Vb = ARq[q][d], KTq[q][d], BTq[q][d], Vbq[q]
            ARn, KTn, BTn, Vbn = f"AR{q}{d}", f"KT{q}{d}", f"BT{q}{d}", f"Vb{q}"
            iv, fn = inv[d], fin[q][d]
            IR = lambda nm: f"i{d}_{nm}"
            FR = lambda nm: f"f{q}{d}_{nm}"
            mS, mC = (0, 2) if d == 0 else (2, 0)
            mSI = masks[:, mS:mS + 2, :].rearrange("p a b -> p (a b)").unsqueeze(1).broadcast_to([128, 4, 256])
            mCb = masks[:, mC, :].unsqueeze(1).broadcast_to([128, 4, 128])
            idb = identb[:].unsqueeze(1).broadcast_to([128, 4, 128])
            for src, srcn, dst, dstn in ((AR[:, :, 0, :], ARn, iv["Atok"], IR("Atok")), (BT[:], BTn, iv["Btok"], IR("Btok")),
                                         (KT[:], KTn, fn["Ktok"], FR("Ktok"))):
                j = nxt("pb")
                pbt = pb[j][:].bitcast(BF16)
                for u in range(4):
                    P.tr(pbt[:, u * 128:(u + 1) * 128], src[:, u, :], identb[:], [srcn, "identb"], [f"pb{j}"])
                P.cp("scalar", dst[:].rearrange("p u x -> p (u x)"), pbt[:, 0:512], [f"pb{j}"], [dstn])
            yield
            mSI2 = masks[:, mS:mS + 2, :].unsqueeze(1).broadcast_to([128, 2, 2, 128])
            for lhs, lhsn, dst, dstn in ((BT, BTn, iv["MQ"], IR("MQ")), (KT, KTn, fn["NP"], FR("NP"))):
                a_step(lambda u, o, on: P.mm(o, lhs[:, u, :], AR[:, u, :, :].rearrange("p a x -> p (a x)"), True, True, [lhsn, ARn], [on]),
                       lambda us, pv, on: P.tt("vector", dst[:, us, :].rearrange("p u (a x) -> p u a x", a=2), pv, mSI2, ALU.mult, [on, "masks"], [dstn]))
            j = nxt("pb")
            for u in range(4):
                P.mm(pb[j][:, u * 128:(u + 1) * 128], AR[:, u, 0, :], BT[:, u, :], True, True, [ARn, BTn], [f"pb{j}"])
            cur, curn, nx, nxn = iv["MWa"], IR("MWa"), iv["MWb"], IR("MWb")
            P.tt("vector", cur[:, :, 0, :], u128(pb[j][:]), mCb, ALU.mult, [f"pb{j}", "masks"], [curn])
            yield
            j = nxt("pb")
            for u in range(4):
                P.mm(pb[j][:, u * 128:(u + 1) * 128], iv["MQ"][:, u, 0:128], cur[:, u, 0, :], True, True, [IR("MQ"), curn], [f"pb{j}"])
            P.cp("scalar", nx[:, :, 0, :], u128(pb[j][:]), [f"pb{j}"], [nxn])
            P.tt("gpsimd", nx[:, :, 1, :], cur[:, :, 0, :], idb, ALU.add, [curn, "identb"], [nxn])
            j = nxt("pb")
            for u in range(4):
                P.mm(pb[j][:, u * 128:(u + 1) * 128], cur[:, u, 0, :], iv["MQ"][:, u, 0:128], True, True, [IR("MQ"), curn], [f"pb{j}"])
            curT, curTn, nxT, nxTn = iv["MTa"], IR("MTa"), iv["MTb"], IR("MTb")
            P.cp("scalar", curT[:], u128(pb[j][:]), [f"pb{j}"], [curTn])
            cur, curn, nx, nxn = nx, nxn, cur, curn
            yield
            for lev in range(1, 5):
                def ev_lev(us, pv, on, cur=cur, curn=curn, nx=nx, nxn=nxn):
                    P.cp("scalar", nx[:, us, 0, :], pv[:, :, 0, :], [on], [nxn])
                    P.tt("vector", nx[:, us, 1, :], pv[:, :, 1, :], cur[:, us, 1, :], ALU.add, [on, curn], [nxn])
                a_step(lambda u, o, on, cur=cur, curn=curn, curT=curT, curTn=curTn:
                       P.mm(o, curT[:, u, :], cur[:, u, :, :].rearrange("p a x -> p (a x)"), True, True, [curTn, curn], [on]), ev_lev)
                j = nxt("pb")
                for u in range(4):
                    P.mm(pb[j][:, u * 128:(u + 1) * 128], cur[:, u, 0, :], curT[:, u, :], True, True, [curn, curTn], [f"pb{j}"])
                P.cp("scalar", nxT[:], u128(pb[j][:]), [f"pb{j}"], [nxTn])
                cur, curn, nx, nxn = nx, nxn, cur, curn
                curT, curTn, nxT, nxTn = nxT, nxTn, curT, curTn
                yield
            j = nxt("pb")
            for u in range(4):
                P.mm(pb[j][:, u * 128:(u + 1) * 128], curT[:, u, :], cur[:, u, 1, :], True, True, [curTn, curn], [f"pb{j}"])
            P.tt("vector", nx[:, :, 1, :], u128(pb[j][:]), cur[:, :, 1, :], ALU.add, [f"pb{j}", curn], [nxn])
            W6, W6n = nx, nxn
            j = nxt("pb")
            for u in range(4):
                P.mm(pb[j][:, u * 64:(u + 1) * 64], fn["NP"][:, u, 0:128], Vb[:, u, :], True, True, [FR("NP"), Vbn], [f"pb{j}"])
            P.cp("scalar", fn["NVb"][:].rearrange("p u x -> p (u x)"), pb[j][:, 0:256], [f"pb{j}"], [FR("NVb")])
            yield
            def mm_d(u, o, on):
                P.mm(o[:, 0:128], W6[:, u, 1, :], iv["MQ"][:, u, 128:256], True, True, [W6n, IR("MQ")], [on])
                P.mm(o[:, 128:256], W6[:, u, 1, :], iv["Btok"][:, u, :], True, True, [W6n, IR("Btok")], [on])
            a_step(mm_d, lambda us, pv, on: P.cp("scalar", fn["XW"][:, us, :].rearrange("p u (a x) -> p u a x", a=2), pv, [on], [FR("XW")]))
            yield
            def ev_f(us, pv, on):
                P.tt("vector", fn["GY"][:, us, :], pv[:, :, 0, :], AR[:, us, 1, :], ALU.add, [on, ARn], [FR("GY")])
                P.tt("vector", fn["GS"][:, us, :], pv[:, :, 1, :], idb[:, 0:2, :], ALU.add, [on, "identb"], [FR("GS")])
            a_step(lambda u, o, on: P.mm(o, iv["Atok"][:, u, :], fn["XW"][:, u, :], True, True, [IR("Atok"), FR("XW")], [on]), ev_f)
            yield

        def finish(ti, oc, q):
            isctx, idx = RW_ORDER1[ti]
            tg = 16 if isctx else idx
            sf, sb_ = fin[q]
            F0 = lambda nm: f"f{q}0_{nm}"
            F1 = lambda nm: f"f{q}1_{nm}"
            Vb, Vbn, gam = Vbq[q], f"Vb{q}", gamq[q]
            SFR = ("Sf", oc)
            for u in range(4):
                yo = pf[:, u * 64:(u + 1) * 64]
                P.mm(yo, sf["NP"][:, u, 128:256], Vb[:, u, :], True, False, [F0("NP"), Vbn], ["pf"])
                P.mm(yo, sf["XW"][:, u, 0:128], sf["NVb"][:, u, :], False, False, [F0("XW"), F0("NVb")], ["pf"])
                P.mm(yo, sb_["NP"][:, u, 128:256], Vb[:, u, :], False, False, [F1("NP"), Vbn], ["pf"])
                P.mm(yo, sb_["XW"][:, u, 0:128], sb_["NVb"][:, u, :], False, False, [F1("XW"), F1("NVb")], ["pf"])
                P.mm(yo, sf["GY"][:, u, :], Sf[:, oc, :], False, True, [F0("GY"), SFR], ["pf"])
                so = pf[:, 256:320]
                P.mm(so, sf["Ktok"][:, u, :], Vb[:, u, :], True, False, [F0("Ktok"), Vbn], ["pf"])
                P.mm(so, sf["XW"][:, u, 128:256], sf["NVb"][:, u, :], False, False, [F0("XW"), F0("NVb")], ["pf"])
                P.mm(so, sf["GS"][:, u, :], Sf[:, oc, :], False, True, [F0("GS"), SFR], ["pf"])
                P.ts("vector", Sf[:, oc, :], so, gam[0][:, u:u + 1], None, ALU.mult, None, ["pf", f"gam{q}0"], [SFR])
                yield
            P.cp("scalar", YPs[:].rearrange("p u x -> p (u x)"), pf[:, 0:256], ["pf"], ["YPs"])
            P.dma("sync", S["yp"][tg, oc], YPs[:].rearrange("p u x -> p (u x)"), reads=["YPs"], writes=[("yp", tg, oc)], sem="YPs")
            j = nxt("pb")
            for u in range(4):
                so = pb[j][:, u * 64:(u + 1) * 64]
                P.mm(so, sb_["Ktok"][:, u, :], Vb[:, u, :], True, False, [F1("Ktok"), Vbn], [f"pb{j}"])
                P.mm(so, sb_["XW"][:, u, 128:256], sb_["NVb"][:, u, :], False, True, [F1("XW"), F1("NVb")], [f"pb{j}"])
            P.cp("scalar", SAs[:].rearrange("p u x -> p (u x)"), pb[j][:, 0:256], [f"pb{j}"], ["SAs"])
            P.dma("sync", S["sadd"][tg, oc], SAs[:].rearrange("p u x -> p (u x)"), reads=["SAs"], writes=[("sadd", tg, oc)], sem="SAs")
            P.dma("sync", S["gyb"][tg, oc], sb_["GY"][:].rearrange("p u x -> p (u x)"), reads=[F1("GY")], writes=[("gyb", tg, oc)], sem=F1("GY"))
            P.dma("sync", S["gsb"][tg, oc], sb_["GS"][:].rearrange("p u x -> p (u x)"), reads=[F1("GS")], writes=[("gsb", tg, oc)], sem=F1("GS"))
            yield

        NT = len(RW_ORDER1)
        NJ = NT * 8
        done = {"prep": set(), "c0": set(), "c1": set(), "fin": set(), "tprep": set()}

        def stream_P():
            for ti in range(NT):
                yield ("tprep", ti, lambda ti=ti: (ti == 0 or ("prep", (ti - 1) * 8 + 7) in donef), lambda ti=ti: tprep(ti))
                for oc in range(8):
                    k = ti * 8 + oc
                    yield ("prep", k, lambda k=k: (k < 2 or (("c0", k - 2) in donef and ("c1", k - 2) in donef and ("fin", k - 2) in donef)),
                           lambda ti=ti, oc=oc, k=k: prep(ti, oc, k % 2))

        def stream_C(d):
            for k in range(NJ):
                ti, oc = divmod(k, 8)
                yield (f"c{d}", k, lambda k=k: (("prep", k) in donef and (k < 2 or ("fin", k - 2) in donef)),
                       lambda ti=ti, oc=oc, k=k: chain(ti, oc, k % 2, d))

        def stream_F():
            for k in range(NJ):
                ti, oc = divmod(k, 8)
                yield ("fin", k, lambda k=k: (("c0", k) in donef and ("c1", k) in donef),
                       lambda ti=ti, oc=oc, k=k: finish(ti, oc, k % 2))

        donef = set()
        load_hh(0)
        streams = [stream_P(), stream_C(0), stream_C(1), stream_F()]
        cur = [None] * 4
        pend = [None] * 4
        alive = [True] * 4
        while any(alive):
            progressed = False
            for si in range(4):
                if not alive[si]:
                    continue
                if cur[si] is None:
                    if pend[si] is None:
                        try:
                            pend[si] = next(streams[si])
                        except StopIteration:
                            alive[si] = False
                            continue
                    kind, k, ready, mk = pend[si]
                    if not ready():
                        continue
                    cur[si] = (kind, k, mk())
                    pend[si] = None
                kind, k, gen = cur[si]
                try:
                    next(gen)
                    progressed = True
                except StopIteration:
                    donef.add((kind, k))
                    cur[si] = None
                    progressed = True
            assert progressed or not any(alive), "scheduler stuck"


def stage_rwkv2(P, io, G, S, src, xa):
    vec, identb = G["vec"], G["identb"]
    GN_EPS = 64e-5
    with P.phase("rwkv2"):
        wo = P.sb([64, 16, 1024], BF16)
        P.dma("gpsimd", wo[:], io["rwkv_wo"].rearrange("(h v) f -> v h f", v=64), writes=["wo"], sem="wo")
        lnw = P.sb([128, 8, 64], F32)
        lnb = P.sb([128, 8, 64], F32)
        P.dma("sync", lnw[:], io["lnw_st"], writes=["lnw"], sem="lnw")
        P.dma("sync", lnb[:], io["lnb_st"], writes=["lnb"], sem="lnb")
        big = {}
        for nm in ("yp", "sadd", "vst", "gst"):
            big[nm] = [P.sb([128, 8, 256], F32, f"l_{nm}{b}") for b in range(2)]
        for nm in ("gyb", "gsb"):
            big[nm] = [P.sb([128, 8, 512], BF16, f"l_{nm}{b}") for b in range(2)]
        gamb = [P.sb([128, 8, 4], F32) for _ in range(2)]
        bon = [P.sb([128, 8, 4], F32) for _ in range(2)]
        xt = [P.sb([128, 8, 256], F32) for _ in range(2)]
        Sb = P.sb([128, 8, 64], BF16)
        ysb = P.sb([128, 8, 64], F32)
        ysq = P.sb([128, 8, 64], F32)
        tmpS = P.sb([128, 8, 64], F32)
        yn = P.sb([128, 8, 64], F32)
        bv = P.sb([128, 8, 64], F32)
        ob = P.sb([128, 8, 64], BF16)
        st = {nm: P.sb([128, 8], F32, "g_" + nm) for nm in ("s1", "s2", "mean", "msq", "var", "lnv", "rstd")}
        OT = P.sb([64, 16, 256], BF16)
        py = [P.ps([128, 512], F32) for _ in range(2)]
        pS = P.ps([128, 512], F32)
        ptr = P.ps([128, 1024], F32)
        pw = [P.ps([128, 512], F32) for _ in range(2)]
        P.memset("gpsimd", Sb[:], 0.0, ["Sb"])

        def load(k):
            isctx, idx = RW_ORDER2[k]
            tg = 16 if isctx else idx
            b = k % 2
            for nm in ("yp", "sadd", "vst", "gst", "gyb", "gsb"):
                P.dma("sync", big[nm][b][:], S[nm][tg].rearrange("o p x -> p o x"), writes=[f"{nm}{b}"], sem=f"{nm}{b}")
            P.dma("sync", gamb[b][:].rearrange("p a b -> p (a b)"), S["gamb"][tg], writes=[f"gamb{b}"], sem=f"gamb{b}")
            P.dma("sync", bon[b][:].rearrange("p a b -> p (a b)"), S["bon"][tg], writes=[f"bon{b}"], sem=f"bon{b}")
            c0 = T if isctx else idx * 256
            P.dma("sync", xt[b][:], fm(src[:, c0:c0 + 256]), writes=[f"xt{b}"], sem=f"xt{b}")

        load(0)
        for k, (isctx, idx) in enumerate(RW_ORDER2):
            b = k % 2
            if k + 1 < len(RW_ORDER2):
                load(k + 1)
            c0 = T if isctx else idx * 256
            _, _, gates = mod_scalars(G, 0, 0, isctx)
            bc = lambda ap: ap.unsqueeze(2).broadcast_to([128, 8, 64])
            def chain_part(u):
                us = slice(u * 64, (u + 1) * 64)
                q_ = u % 2
                for oc in range(8):
                    P.mm(py[q_][:, oc * 64:(oc + 1) * 64], big["gyb"][b][:, oc, u * 128:(u + 1) * 128], Sb[:, oc, :], True, True, [f"gyb{b}", "Sb"], [f"py{q_}"])
                for oc in range(8):
                    P.mm(pS[:, oc * 64:(oc + 1) * 64], big["gsb"][b][:, oc, u * 128:(u + 1) * 128], Sb[:, oc, :], True, True, [f"gsb{b}", "Sb"], ["pS"])
                pS3 = pS[:].rearrange("p (o v) -> p o v", v=64)
                P.tt("vector", tmpS[:], pS3, big["sadd"][b][:, :, us], ALU.add, ["pS", f"sadd{b}"], ["tmpS"])
                P.tt("vector", Sb[:], tmpS[:], bc(gamb[b][:, :, u]), ALU.mult, ["tmpS", f"gamb{b}"], ["Sb"])

            def read_part(u):
                us = slice(u * 64, (u + 1) * 64)
                q_ = u % 2
                py3 = py[q_][:].rearrange("p (o v) -> p o v", v=64)
                P.tt("vector", ysb[:], py3, big["yp"][b][:, :, us], ALU.add, [f"py{q_}", f"yp{b}"], ["ysb"])
                P.op("vector", lambda e: e.tensor_reduce(out=st["s1"][:], in_=ysb[:], axis=AX.X, op=ALU.add), ["ysb"], ["s1"])
                P.tt("gpsimd", ysq[:], ysb[:], ysb[:], ALU.mult, ["ysb"], ["ysq"])
                P.op("vector", lambda e: e.tensor_reduce(out=st["s2"][:], in_=ysq[:], axis=AX.X, op=ALU.add), ["ysq"], ["s2"])
                P.ts("vector", st["mean"][:], st["s1"][:], 1.0 / 64, None, ALU.mult, None, ["s1"], ["mean"])
                P.tt("vector", st["msq"][:], st["mean"][:], st["mean"][:], ALU.mult, ["mean"], ["msq"])
                P.stt(st["var"][:], st["s2"][:], 1.0 / 64, st["msq"][:], ALU.mult, ALU.subtract, ["s2", "msq"], ["var"])
                P.act(st["lnv"][:], st["var"][:], AF.Ln, ["var"], ["lnv"], bias=GN_EPS)
                P.act(st["rstd"][:], st["lnv"][:], AF.Exp, ["lnv"], ["rstd"], scale=-0.5)
                P.tt("gpsimd", yn[:], ysb[:], bc(st["mean"][:]), ALU.subtract, ["ysb", "mean"], ["yn"])
                P.tt("gpsimd", bv[:], big["vst"][b][:, :, us], bc(bon[b][:, :, u]), ALU.mult, [f"vst{b}", f"bon{b}"], ["bv"])
                P.tt("vector", yn[:], yn[:], bc(st["rstd"][:]), ALU.mult, ["yn", "rstd"], ["yn"])
                P.tt("gpsimd", yn[:], yn[:], lnw[:], ALU.mult, ["yn", "lnw"], ["yn"])
                P.tt("vector", yn[:], yn[:], lnb[:], ALU.add, ["yn", "lnb"], ["yn"])
                P.tt("gpsimd", yn[:], yn[:], bv[:], ALU.add, ["yn", "bv"], ["yn"])
                P.tt("vector", ob[:], yn[:], big["gst"][b][:, :, us], ALU.mult, ["yn", f"gst{b}"], ["ob"])
                ptb = ptr[:].bitcast(BF16)
                for oc in range(8):
                    P.tr(ptb[0:64, oc * 128:(oc + 1) * 128], ob[:, oc, :], identb[:], ["ob", "identb"], ["ptr"])
                P.cp("scalar", OT[:, :, us], ptb[0:64, 0:1024].rearrange("p (h t) -> p h t", t=64), ["ptr"], ["OT"])

            chain_part(3)
            for u in range(3, -1, -1):
                if u > 0:
                    chain_part(u - 1)
                read_part(u)
            for oc in range(8):
                j = oc % 2
                for h in range(16):
                    P.mm(pw[j][:, 0:256], wo[:, h, oc * 128:(oc + 1) * 128], OT[:, h, :], h == 0, h == 15, ["wo", "OT"], [f"pw{j}"])
                P.stt(xt[b][:, oc, :], pw[j][:, 0:256], gates[oc], xt[b][:, oc, :], ALU.mult, ALU.add, [f"pw{j}", f"xt{b}", "modv"], [f"xt{b}"])
            P.dma("sync", fm(xa[:, c0:c0 + 256]), xt[b][:], reads=[f"xt{b}"], writes=[("xa", k)], sem=f"xt{b}")


def stage_qkv(P, io, G, hb, qtd, Kz, VA):
    vec, bones, perm = G["vec"], G["bones"], G["perm"]
    with P.phase("qkv"):
        wq = P.sb([128, 8, 1024], BF16)
        wkd = P.sb([128, 8, 512], BF16)
        wv = P.sb([128, 8, 256], BF16)
        P.dma("gpsimd", wq[:], fm(io["attn_wq"]), writes=["wq"], sem="wq")
        P.dma("gpsimd", wkd[:], fm(io["attn_wkd"]), writes=["wkd"], sem="wkd")
        P.dma("gpsimd", wv[:], fm(io["attn_wv"]), writes=["wv"], sem="wv")
        ht = [P.sb([128, 8, 512], BF16) for _ in range(2)]
        cs = [P.sb([128, 512], F32) for _ in range(2)]
        sn = [P.sb([128, 512], F32) for _ in range(2)]
        NB = 2
        qf = [P.sb([128, 512], F32) for _ in range(NB)]
        sqb = [P.sb([128, 512], BF16) for _ in range(NB)]
        lnv = [P.sb([128, 512], F32) for _ in range(NB)]
        rstd = [P.sb([128, 512], F32) for _ in range(NB)]
        qh = [P.sb([128, 512], F32) for _ in range(NB)]
        qhb = [P.sb([128, 512], BF16) for _ in range(NB)]
        t1 = [P.sb([128, 512], F32) for _ in range(NB)]
        t2 = [P.sb([128, 512], F32) for _ in range(NB)]
        qst = [P.sb([128, 8, 512], BF16) for _ in range(2)]
        pp = [P.ps([128, 512], F32) for _ in range(6)]
        cnt = [0, 0]

        def nxt():
            cnt[0] += 1
            return cnt[0] % 6

        P.memset("gpsimd", VA[:], 0.0, ["VA0"])
        P.memset("gpsimd", VA[:].rearrange("p k (j x) -> p k j x", x=65)[:, :, 0:5, 64:65], 1.0, ["VA0"])
        P.memset("gpsimd", Kz[0][64:128, :, :], 0.0, ["Kz0z"])
        P.memset("gpsimd", Kz[1][0:64, :, :], 0.0, ["Kz1z"])
        tiles = ALL_TILES

        def load(i):
            c0, tw, isctx = tiles[i]
            b = i % 2
            P.dma("sync", ht[b][:, :, :tw], fm(hb[:, c0:c0 + tw]), writes=[f"ht{b}"], sem=f"ht{b}")
            if not isctx:
                P.dma("sync", cs[b][:, :tw], io["cosT"][:, c0:c0 + tw], writes=[f"cs{b}"], sem=f"cs{b}")
                P.dma("sync", sn[b][:, :tw], io["sinT"][:, c0:c0 + tw], writes=[f"sn{b}"], sem=f"sn{b}")

        def normrope(wcols, nscal, dsts, b, tw, isctx, wname, dres="dstqk"):
            cnt[1] += 1
            n = cnt[1] % NB
            i = nxt()
            for c in range(8):
                P.mm(pp[i][:, :tw], wcols(c), ht[b][:, c, :tw], c == 0, c == 7, [wname, f"ht{b}"], [f"pp{i}"])
            P.cp("scalar", qf[n][:, :tw], pp[i][:, :tw], [f"pp{i}"], [f"qf{n}"])
            P.act(sqb[n][:, :tw], qf[n][:, :tw], AF.Square, [f"qf{n}"], [f"sqb{n}"])
            i = nxt()
            P.mm(pp[i][:, :tw], bones[:], sqb[n][:, :tw], True, True, ["bones", f"sqb{n}"], [f"pp{i}"])
            P.act(lnv[n][:, :tw], pp[i][:, :tw], AF.Ln, [f"pp{i}"], [f"lnv{n}"], bias=1e-6, scale=1.0 / 64)
            P.act(rstd[n][:, :tw], lnv[n][:, :tw], AF.Exp, [f"lnv{n}"], [f"rstd{n}"], scale=-0.5)
            P.stt(qh[n][:, :tw], qf[n][:, :tw], nscal, rstd[n][:, :tw], ALU.mult, ALU.mult, [f"qf{n}", f"rstd{n}", "vec"], [f"qh{n}"])
            if isctx:
                for dst, sl in dsts:
                    P.cp("gpsimd", dst, qh[n][sl, :tw], [f"qh{n}"], [dres])
                return
            P.cp("gpsimd", qhb[n][:, :tw], qh[n][:, :tw], [f"qh{n}"], [f"qhb{n}"])
            i = nxt()
            P.mm(pp[i][:, :tw], perm[:], qhb[n][:, :tw], True, True, ["perm", f"qhb{n}"], [f"pp{i}"])
            P.tt("gpsimd", t1[n][:, :tw], qh[n][:, :tw], cs[b][:, :tw], ALU.mult, [f"qh{n}", f"cs{b}"], [f"t1{n}"])
            P.tt("vector", t2[n][:, :tw], pp[i][:, :tw], sn[b][:, :tw], ALU.mult, [f"pp{i}", f"sn{b}"], [f"t2{n}"])
            for dst, sl in dsts:
                P.tt("gpsimd", dst, t1[n][sl, :tw], t2[n][sl, :tw], ALU.add, [f"t1{n}", f"t2{n}"], [dres])

        ALLP = slice(0, 128)
        load(0)
        for i, (c0, tw, isctx) in enumerate(tiles):
            b = i % 2
            if i + 1 < len(tiles):
                load(i + 1)
            if not isctx:
                for oc in range(8):
                    normrope(lambda c: wq[:, c, oc * 128:(oc + 1) * 128], vec[:, 19, oc:oc + 1], [(qst[b][:, oc, :tw], ALLP)], b, tw, False, "wq",
                             dres=(f"qst{b}", oc))
                P.dma("sync", fm(qtd[:, c0:c0 + tw]), qst[b][:, :, :tw], reads=[(f"qst{b}", oc) for oc in range(8)], writes=[("qtd", i)], sem=f"qst{b}")
            for g in range(4):
                normrope(lambda c: wkd[:, c, g * 128:(g + 1) * 128], vec[:, 20, 0:1],
                         [(Kz[0][0:64, g, c0:c0 + tw], slice(0, 64)), (Kz[1][64:128, g, c0:c0 + tw], slice(64, 128))], b, tw, isctx, "wkd")
            for sub in range(tw // 128):
                kt = c0 // 128 + sub
                j = nxt()
                for c in range(8):
                    P.mm(pp[j][:, 0:256], ht[b][:, c, sub * 128:(sub + 1) * 128], wv[:, c, :], c == 0, c == 7, ["wv", f"ht{b}"], [f"pp{j}"])
                P.cp("scalar", VA[:, kt, 65:325].rearrange("p (g x) -> p g x", x=65)[:, :, 0:64],
                     pp[j][:, 0:256].rearrange("p (g d) -> p g d", d=64), [f"pp{j}", "VA0"], [("VA", kt)])


def stage_attn(P, io, G, qtd, Kz, VA, xa):
    with P.phase("attn"):
        wo = P.sb([128, 8, 1024], BF16)
        P.dma("gpsimd", wo[:], fm(io["attn_wo"]), writes=["wo"], sem="wo")
        sel = P.sb([128, 2, 128], F32)
        P.dma("sync", sel[:], io["c_sel"], writes=["sel"], sem="sel")
        PT = [P.sb([128, 1024], BF16) for _ in range(3)]
        osb = [P.sb([128, 512], F32) for _ in range(2)]
        rb = [P.sb([128, 512], F32) for _ in range(2)]
        xt = P.sb([128, 8, 512], F32)
        QB = [P.sb([128, 8, 512], BF16) for _ in range(2)]
        psS = [P.ps([128, 1024], F32) for _ in range(2)]
        psO = [P.ps([128, 512], F32) for _ in range(2)]
        psB = P.ps([128, 512], F32)
        pX = [P.ps([128, 512], F32) for _ in range(1)]
        _, _, gates = mod_scalars(G, 1, 0, False)
        for k in range(2):
            P.memset("gpsimd", osb[k][:], 0.0, [f"osb{k}"])
        def loadq(qb):
            P.dma("sync", QB[qb % 2][:], fm(qtd[:, qb * 512:(qb + 1) * 512]), writes=[("QT", h, qb) for h in range(16)], sem=f"QB{qb % 2}")

        loadq(0)
        for qb in range(8):
            qsl = slice(qb * 512, (qb + 1) * 512)
            QT = QB[qb % 2]
            if qb + 1 < 8:
                loadq(qb + 1)
            P.dma("sync", xt[:], fm(xa[:, qsl]), writes=["xt"], sem="xt")
            steps = [(h, kp) for h in range(16) for kp in range(17)]

            def S(i):
                h, kp = steps[i]
                g, oc, h2 = h // 4, h // 2, h % 2
                for e_ in range(2):
                    kt = 2 * kp + e_
                    P.mm(psS[i % 2][:, e_ * 512:(e_ + 1) * 512], Kz[h2][:, g, kt * 128:(kt + 1) * 128], QT[:, oc, :], True, True,
                         ["Kz", ("QT", 2 * oc, qb), ("QT", 2 * oc + 1, qb)], [f"psS{i % 2}"])

            def epi_a(h):
                o = h % 2
                P.cp("vector", osb[o][:], psO[o][:], [f"psO{o}"], [f"osb{o}"])

            def epi_b(h):
                oc, h2, o = h // 2, h % 2, h % 2
                hs = slice(h2 * 64, h2 * 64 + 64)
                P.mm(psB[:, :], sel[:, h2, :], osb[o][:], True, True, ["sel", f"osb{o}"], ["psB"])
                P.act(rb[o][hs, :], psB[hs, :], AF.Ln, ["psB"], [f"rb{o}"])
                P.act(rb[o][hs, :], rb[o][hs, :], AF.Exp, [f"rb{o}"], [f"rb{o}"], scale=-1.0)
                P.tt("gpsimd", QT[hs, oc, :], osb[o][hs, :], rb[o][hs, :], ALU.mult, [f"osb{o}", f"rb{o}"], [("QT", h, qb)])

            S(0)
            pend = {}
            for i, (h, kp) in enumerate(steps):
                g, h2, o = h // 4, h % 2, h % 2
                if i + 1 < len(steps):
                    S(i + 1)
                p_ = i % 3
                P.act(PT[p_][:], psS[i % 2][:, :], AF.Exp, [f"psS{i % 2}"], [f"PT{p_}"], scale=0.125)
                v0 = 65 + 65 * g if h2 == 0 else 1 + 65 * g
                for e_ in range(2):
                    kt = 2 * kp + e_
                    P.mm(psO[o][:, :], VA[:, kt, v0:v0 + 128], PT[p_][:, e_ * 512:(e_ + 1) * 512], kt == 0, kt == 33, [f"PT{p_}", "VA"], [f"psO{o}"])
                if kp == 16:
                    epi_a(h)
                    pend[i + 3] = h
                if i in pend:
                    epi_b(pend.pop(i))
            for k in sorted(pend):
                epi_b(pend[k])
            for oc in range(8):
                j = 0
                for c in range(8):
                    P.mm(pX[j][:, :], wo[:, c, oc * 128:(oc + 1) * 128], QT[:, c, :], c == 0, c == 7,
                         ["wo", ("QT", 2 * c, qb), ("QT", 2 * c + 1, qb)], [f"pX{j}"])
                P.stt(xt[:, oc, :], pX[j][:, :], gates[oc], xt[:, oc, :], ALU.mult, ALU.add, [f"pX{j}", "xt", "modv"], ["xt"])
            P.dma("sync", fm(xa[:, qsl]), xt[:], reads=["xt"], writes=[("xa", qb)], sem="xt")


IN_SHAPES = {
    "xin": [D, TT], "cvec": [128, 8, 2], "w_mod": [2, D, 6 * D], "b_mod": [2, 6 * D], "vecs": [128, NV, 8],
    "mlp_w1": [2, D, 4 * D], "mlp_w2": [2, 4 * D, D],
    "rwkv_wr": [D, D], "rwkv_wk": [D, D], "rwkv_wv": [D, D], "rwkv_wo": [D, D],
    "rwkv_w1": [2, D, 64], "rwkv_w2": [2, 64, D], "rwkv_a1": [2, D, 64], "rwkv_a2": [2, 64, D],
    "rwkv_g1": [D, 128], "rwkv_g2": [128, D], "lnw_st": [128, 8, 64], "lnb_st": [128, 8, 64],
    "attn_wq": [D, D], "attn_wkd": [D, 512], "attn_wv": [D, 256], "attn_wo": [D, D],
    "cosT": [128, T], "sinT": [128, T],
    "c_ident": [128, 128], "c_ones": [128, 128], "c_bones": [128, 128], "c_masks": [128, 4, 128],
    "c_perm": [128, 128], "c_rmask": [128, 256], "c_sel": [128, 2, 128],
}


class IO(dict):
    def __init__(self, nc):
        super().__init__()
        self.nc = nc
        self.used = []

    def __missing__(self, k):
        ap = self.nc.dram_tensor(k, IN_SHAPES[k], F32, kind="ExternalInput").ap()
        self[k] = ap
        self.used.append(k)
        return ap

    def scratch(self, name, shape, dtype):
        return self.nc.dram_tensor(name, list(shape), dtype, kind="Internal").ap()

    def output(self, name, shape, dtype=F32):
        return self.nc.dram_tensor(name, list(shape), dtype, kind="ExternalOutput").ap()


def build(stages="all", dbg=None):
    nc = bass.Bass("TRN2", target_bir_lowering=False)
    io = IO(nc)
    P = Prog(nc)
    G = {}
    outs = {}
    stage_init(P, io, G)
    xa = io.scratch("xa", [D, TT], F32)
    hb = io.scratch("hb", [D, TT], BF16)
    if stages == "t_mlp":
        outs["dbg_h"] = io.output("dbg_h", [D, TT], BF16)
        stage_norm(P, io, G, "n_t", io["xin"], ALL_TILES,
                   lambda ic: mod_scalars(G, 0, 1, ic)[0], lambda ic: mod_scalars(G, 0, 1, ic)[1],
                   lambda c0, tw, ic: fm(hb[:, c0:c0 + tw]), BF16)
        with P.phase("copy"):
            P.dma("sync", xa, io["xin"], writes=["xa"], sem="cpa")
            P.dma("sync", outs["dbg_h"], hb, writes=["o"], sem="cpb")
        stage_mlp(P, io, G, 0, ALL_TILES, xa, hb)
        outs["y"] = io.output("y", [D, TT])
        fin = [G["vec"][:, 4, c:c + 1] for c in range(8)]
        stage_norm(P, io, G, "final", xa, ALL_TILES, lambda ic: fin, lambda ic: None,
                   lambda c0, tw, ic: fm(outs["y"][:, c0:c0 + tw]), F32)
    if stages in ("all", "l0", "l1pre"):
        hp = io.scratch("hp", [D, 4608], F32)
        S = rw_scratch(io)
        with P.phase("zpad"):
            z = P.sb([128, 8, 64], F32)
            P.memset("vector", z[:], 0.0, ["z"])
            for k, o in enumerate((0, 64 + T, 4224, 4288 + C)):
                P.dma("sync", fm(hp[:, o:o + 64]), z[:], reads=["z"], writes=[("hpz", k)], sem=f"z{k}")

        def hdst(c0, tw, ic):
            o = 4288 if ic else 64 + c0
            return fm(hp[:, o:o + tw])

        def hbdst(c0, tw, ic):
            return fm(hb[:, c0:c0 + tw])

        def ms(l, kind, which):
            return lambda ic: mod_scalars(G, l, kind, ic)[which]

        stage_norm(P, io, G, "n_mix0", io["xin"], ALL_TILES, ms(0, 0, 0), ms(0, 0, 1), hdst, F32)
        stage_rwkv1(P, io, G, hp, S)
        stage_rwkv2(P, io, G, S, io["xin"], xa)
        stage_norm(P, io, G, "n_mlp0", xa, ALL_TILES, ms(0, 1, 0), ms(0, 1, 1), hbdst, BF16)
        stage_mlp(P, io, G, 0, ALL_TILES, xa, hb)
        if stages == "l0":
            outs["y"] = io.output("y", [D, TT])
            with P.phase("copyout"):
                P.dma("sync", outs["y"], xa, writes=["o"], sem="cpa")
        else:
            stage_norm(P, io, G, "n_mix1", xa, ALL_TILES, ms(1, 0, 0), ms(1, 0, 1), hbdst, BF16)
            with P.scope():
                QT = io.scratch("qtd", [D, T], BF16)
                Kz = [P.ssb([128, 4, TT], BF16, f"Kz{k}") for k in range(2)]
                VA = P.ssb([128, 34, 390], BF16, "VA")
                stage_qkv(P, io, G, hb, QT, Kz, VA)
                stage_attn(P, io, G, QT, Kz, VA, xa)
            if stages == "l1pre":
                outs["y"] = io.output("y", [D, TT])
                with P.phase("copyout"):
                    P.dma("sync", outs["y"], xa, writes=["o"], sem="cpa")
            else:
                stage_norm(P, io, G, "n_mlp1", xa, LAT_TILES, ms(1, 1, 0), ms(1, 1, 1), hbdst, BF16)
                stage_mlp(P, io, G, 1, LAT_TILES, xa, hb)
                outs["y"] = io.output("y", [D, T])
                fin = [G["vec"][:, 4, c:c + 1] for c in range(8)]
                stage_norm(P, io, G, "final", xa, LAT_TILES, lambda ic: fin, lambda ic: None,
                           lambda c0, tw, ic: fm(outs["y"][:, c0:c0 + tw]), F32)
    if stages == "t_rwkv":
        hp = io.scratch("hp", [D, 4608], F32)
        S = rw_scratch(io)
        with P.phase("zpad"):
            z = P.sb([128, 8, 64], F32)
            P.memset("vector", z[:], 0.0, ["z"])
            for k, o in enumerate((0, 64 + T, 4224, 4288 + C)):
                P.dma("sync", fm(hp[:, o:o + 64]), z[:], reads=["z"], writes=[("hpz", k)], sem=f"z{k}")
        def hdst(c0, tw, ic):
            o = 4288 if ic else 64 + c0
            return fm(hp[:, o:o + tw])
        stage_norm(P, io, G, "n_mix0", io["xin"], ALL_TILES,
                   lambda ic: mod_scalars(G, 0, 0, ic)[0], lambda ic: mod_scalars(G, 0, 0, ic)[1], hdst, F32)
        stage_rwkv1(P, io, G, hp, S)
        stage_rwkv2(P, io, G, S, io["xin"], xa)
        outs["y"] = io.output("y", [D, TT])
        with P.phase("copyout"):
            P.dma("sync", outs["y"], xa, writes=["o"], sem="cpa")
    P.close()
    return nc, io.used, list(outs.keys()), P


def fmv(v):
    return np.ascontiguousarray(np.asarray(v, np.float32).reshape(8, 128).T)


def host_consts():
    c = {}
    c["c_ident"] = np.eye(128, dtype=np.float32)
    c["c_ones"] = np.ones((128, 128), np.float32)
    blk = np.zeros((128, 128), np.float32)
    blk[:64, :64] = 1
    blk[64:, 64:] = 1
    c["c_bones"] = blk
    i = np.arange(64)
    us = (i[:, None] < i[None, :]).astype(np.float32)
    ui = (i[:, None] <= i[None, :]).astype(np.float32)
    m = np.zeros((128, 4, 128), np.float32)
    for k, mk in enumerate([us, ui, us.T, ui.T]):
        m[:64, k, :64] = mk
        m[64:, k, 64:] = mk
    c["c_masks"] = m
    Pm = np.zeros((128, 128), np.float32)
    for d in range(128):
        if d % 32 < 16:
            Pm[d, d + 16] = -1.0
        else:
            Pm[d, d - 16] = 1.0
    c["c_perm"] = np.ascontiguousarray(Pm.T)
    sel = np.zeros((128, 2, 128), np.float32)
    sel[64, 0, :] = 1.0
    sel[63, 1, :] = 1.0
    c["c_sel"] = sel
    rm = np.ones((128, 256), np.float32)
    rm[:, ::64] = 0
    c["c_rmask"] = rm
    t = np.arange(T)
    row = (t // 64).astype(np.float32)
    col = (t % 64).astype(np.float32)
    freqs = (np.float32(10000.0) ** (-np.arange(0, 32, 2, dtype=np.float32) / np.float32(32))).astype(np.float32)
    ang = np.zeros((64, T), np.float32)
    for d in range(64):
        pos = row if d < 32 else col
        ang[d] = pos * freqs[d % 16]
    c["cosT"] = np.ascontiguousarray(np.concatenate([np.cos(ang), np.cos(ang)], 0).astype(np.float32))
    c["sinT"] = np.ascontiguousarray(np.concatenate([np.sin(ang), np.sin(ang)], 0).astype(np.float32))
    return c


def host_inputs(inp, b):
    f = lambda k: np.asarray(inp[k], np.float32)
    d = {}
    d["xin"] = np.ascontiguousarray(np.concatenate([f("x")[b].T, f("ctx")[b].T], axis=1))
    d["cvec"] = np.ascontiguousarray(np.stack([fmv(f("c")[b]), fmv(f("c_ctx"))], axis=-1))
    return d


def host_shared(inp):
    f = lambda k: np.asarray(inp[k], np.float32)
    s = dict(host_consts())
    s["w_mod"] = f("w_mod")
    s["b_mod"] = f("b_mod")
    vl = [f("norm_mix")[0], f("norm_mix")[1], f("norm_mlp")[0], f("norm_mlp")[1], f("final_norm")]
    vl += [f("rwkv_mu")[0, j] for j in range(6)]
    vl += [f("rwkv_w0")[0, 0], f("rwkv_w0")[0, 1], f("rwkv_a0")[0, 0], f("rwkv_a0")[0, 1]]
    vl += [f("rwkv_k_k")[0], f("rwkv_k_a")[0], np.zeros(D, np.float32), f("rwkv_r_k")[0].reshape(-1)]
    vl += [np.tile(f("attn_q_norm")[0], 16), np.tile(f("attn_k_norm")[0], 16)]
    assert len(vl) == NV
    s["vecs"] = np.ascontiguousarray(np.stack([fmv(v) for v in vl], axis=1))
    s["mlp_w1"] = f("mlp_w1")
    s["mlp_w2"] = f("mlp_w2")
    for k in ("wr", "wk", "wv", "wo", "w1", "w2", "a1", "a2", "g1", "g2"):
        s["rwkv_" + k] = f("rwkv_" + k)[0]
    lw = f("rwkv_ln_w")[0].reshape(8, 2, 64)
    lb = f("rwkv_ln_b")[0].reshape(8, 2, 64)
    s["lnw_st"] = np.ascontiguousarray(np.repeat(lw.transpose(1, 0, 2), 64, axis=0))
    s["lnb_st"] = np.ascontiguousarray(np.repeat(lb.transpose(1, 0, 2), 64, axis=0))
    wqkv = f("attn_wqkv")[0]
    s["attn_wq"] = np.ascontiguousarray(wqkv[:, :1024])
    wk = wqkv[:, 1024:1280].reshape(D, 4, 64)
    s["attn_wkd"] = np.ascontiguousarray(np.concatenate([wk, wk], axis=2).reshape(D, 512))
    s["attn_wv"] = np.ascontiguousarray(wqkv[:, 1280:1536])
    s["attn_wo"] = f("attn_wo")[0]
    return s


_CACHE = {}


def kernel(**inputs):
    if "prog" not in _CACHE:
        _CACHE["prog"] = build("all")
    nc, used, outnames, _ = _CACHE["prog"]
    shared = host_shared(inputs)
    in_maps = []
    for b in range(NCORES):
        hi = host_inputs(inputs, b)
        hi.update(shared)
        in_maps.append({k: hi[k] for k in used})
    res = run_bass_kernel_spmd(nc, in_maps, core_ids=list(range(NCORES)))
    out = np.stack([np.ascontiguousarray(res.results[b]["y"].T) for b in range(NCORES)], axis=0)
    return out.astype(np.float32)
```

```python
from contextlib import ExitStack, contextmanager
import numpy as np
import concourse.bass as bass
import concourse.mybir as mybir
from concourse.bass_utils import run_bass_kernel_spmd

F32 = mybir.dt.float32
BF16 = mybir.dt.bfloat16
AF = mybir.ActivationFunctionType
ALU = mybir.AluOpType
AX = mybir.AxisListType

D = 1024
T = 4096
C = 256
TT = T + C
NCORES = 8
C0 = float(np.exp(-0.5))
NV = 21
ENGS = ("tensor", "vector", "scalar", "gpsimd", "sync")


class Prog:
    def __init__(self, nc):
        self.nc = nc
        self.ges = ExitStack()
        self.sems = {}
        self.cnt = {}
        self.dpool = {False: [], True: []}
        self.seen = {e: {} for e in ENGS}
        self.n = 0
        self.pes = None
        self.total_ops = 0

    def _alloc(self, es, fn, shape, dtype, name):
        self.n += 1
        return es.enter_context(fn(name or f"t{self.n}", list(shape), dtype))

    def gsb(self, shape, dtype, name=None):
        return self._alloc(self.ges, self.nc.sbuf_tensor, shape, dtype, name)

    def sb(self, shape, dtype, name=None):
        return self._alloc(self.pes, self.nc.sbuf_tensor, shape, dtype, name)

    @contextmanager
    def scope(self):
        self.ses = ExitStack()
        yield self
        self.ses.close()
        self.ses = None

    def ssb(self, shape, dtype, name=None):
        return self._alloc(self.ses, self.nc.sbuf_tensor, shape, dtype, name)

    def ps(self, shape, dtype, name=None):
        return self._alloc(self.pes, self.nc.psum_tensor, shape, dtype, name)

    @contextmanager
    def phase(self, name):
        self.ops = []
        self.last_w = {}
        self.readers = {}
        self.last_dma = {}
        self.pes = ExitStack()
        self.pname = name
        yield self
        self._emit()
        self.pes.close()
        self.pes = None

    def _deps(self, reads, writes):
        deps = {}
        for r in reads:
            if r in self.last_w:
                deps.setdefault(self.last_w[r], set()).add("RAW")
        for w in writes:
            if w in self.last_w:
                deps.setdefault(self.last_w[w], set()).add("WAW")
            for rd in self.readers.get(w, ()):
                deps.setdefault(rd, set()).add("WAR")
        idx = len(self.ops)
        for r in reads:
            self.readers.setdefault(r, []).append(idx)
        for w in writes:
            self.last_w[w] = idx
            self.readers[w] = []
        return deps

    def op(self, eng, fn, reads=(), writes=()):
        deps = self._deps(tuple(reads), tuple(writes))
        self.ops.append(dict(eng=eng, fn=fn, deps=deps, dma=None))
        return len(self.ops) - 1

    def dma(self, queue, out, in_, reads=(), writes=(), sem=None):
        deps = self._deps(tuple(reads), tuple(writes))
        prev = self.last_dma.get(sem)
        if prev is not None:
            deps.setdefault(prev, set()).add("SER")
        idx = len(self.ops)
        self.last_dma[sem] = idx
        self.ops.append(dict(eng=queue, fn=lambda e: e.dma_start(out=out, in_=in_), deps=deps, dma=sem))
        return idx

    def _emit(self):
        nc = self.nc
        ops = self.ops
        if self.last_dma:
            ops.append(dict(eng="sync", fn=None, deps={i: {"FIN"} for i in self.last_dma.values()}, dma=None))
        self.total_ops += len(ops)

        def needs_wait(x, d, kinds):
            if d["dma"] is not None or x["dma"] is not None:
                return True
            if d["eng"] != x["eng"]:
                return True
            if x["eng"] == "tensor":
                return False
            return bool(kinds & {"RAW", "FIN"})

        signal = [False] * len(ops)
        for x in ops:
            for di, kinds in x["deps"].items():
                d = ops[di]
                if d["dma"] is None and needs_wait(x, d, kinds):
                    signal[di] = True
        dkeys = {}
        nk = {False: 0, True: 0}
        for o in ops:
            if o["dma"] is not None and o["dma"] not in dkeys:
                sw = o["eng"] == "gpsimd"
                dkeys[o["dma"]] = (sw, nk[sw])
                nk[sw] += 1
        for sw in (False, True):
            while len(self.dpool[sw]) < nk[sw]:
                h = self.ges.enter_context(nc.semaphore(f"dq{int(sw)}_{len(self.dpool[sw])}"))
                self.dpool[sw].append([h, 0])
        for e in ENGS:
            if e not in self.sems:
                self.sems[e] = self.ges.enter_context(nc.semaphore(f"e_{e}"))
        token = [None] * len(ops)
        for i, o in enumerate(ops):
            if o["dma"] is not None:
                dk = dkeys[o["dma"]]
                slot = self.dpool[dk[0]][dk[1]]
                slot[1] += 16
                token[i] = (("d", dk), slot[1])
            elif signal[i]:
                self.cnt[o["eng"]] = self.cnt.get(o["eng"], 0) + 1
                token[i] = (("e", o["eng"]), self.cnt[o["eng"]])
        per_eng = {e: [] for e in ENGS}
        for i, o in enumerate(ops):
            per_eng[o["eng"]].append(i)

        def semh(key):
            return self.dpool[key[1][0]][key[1][1]][0] if key[0] == "d" else self.sems[key[1]]

        def run(engname, eng):
            seen = self.seen[engname]
            for i in per_eng[engname]:
                o = ops[i]
                waits = {}
                for di, kinds in o["deps"].items():
                    d = ops[di]
                    if not needs_wait(o, d, kinds):
                        continue
                    key, val = token[di]
                    if waits.get(key, 0) < val:
                        waits[key] = val
                for key, val in waits.items():
                    if seen.get(key, 0) >= val:
                        continue
                    seen[key] = val
                    eng.wait_ge(semh(key), val)
                if o["fn"] is None:
                    continue
                ins = o["fn"](eng)
                if o["dma"] is not None:
                    ins.then_inc(semh(token[i][0]), 16)
                elif signal[i]:
                    ins.then_inc(self.sems[engname], 1)

        with nc.Block() as block:
            @block.sync
            def _(e):
                run("sync", e)

            @block.tensor
            def _(e):
                run("tensor", e)

            @block.vector
            def _(e):
                run("vector", e)

            @block.scalar
            def _(e):
                run("scalar", e)

            @block.gpsimd
            def _(e):
                run("gpsimd", e)

    def close(self):
        self.ges.close()

    def mm(self, out, lhsT, rhs, start, stop, r, w):
        self.op("tensor", lambda e: e.matmul(out, lhsT=lhsT, rhs=rhs, start=start, stop=stop), r, w)

    def tr(self, out, in_, ident, r, w):
        self.op("tensor", lambda e: e.transpose(out, in_, ident), r, w)

    def tt(self, eng, out, in0, in1, op, r, w):
        self.op(eng, lambda e: e.tensor_tensor(out=out, in0=in0, in1=in1, op=op), r, w)

    def ts(self, eng, out, in0, s1, s2, op0, op1, r, w):
        if op1 is None:
            self.op(eng, lambda e: e.tensor_scalar(out=out, in0=in0, scalar1=s1, scalar2=None, op0=op0), r, w)
        else:
            self.op(eng, lambda e: e.tensor_scalar(out=out, in0=in0, scalar1=s1, scalar2=s2, op0=op0, op1=op1), r, w)

    def stt(self, out, in0, scalar, in1, op0, op1, r, w):
        self.op("vector", lambda e: e.scalar_tensor_tensor(out=out, in0=in0, scalar=scalar, in1=in1, op0=op0, op1=op1), r, w)

    def act(self, out, in_, func, r, w, bias=None, scale=None):
        kw = {}
        if bias is not None:
            kw["bias"] = bias
        if scale is not None:
            kw["scale"] = scale
        self.op("scalar", lambda e: e.activation(out=out, in_=in_, func=func, **kw), r, w)

    def cp(self, eng, out, in_, r, w):
        if eng == "scalar":
            self.op(eng, lambda e: e.activation(out=out, in_=in_, func=AF.Copy), r, w)
        else:
            self.op(eng, lambda e: e.tensor_copy(out=out, in_=in_), r, w)

    def memset(self, eng, ap, val, w):
        self.op(eng, lambda e: e.memset(ap, val), (), w)


def fm(ap2d):
    return ap2d.rearrange("(c p) n -> p c n", p=128)


LAT_TILES = [(i * 512, 512, False) for i in range(8)]
ALL_TILES = LAT_TILES + [(T, 256, True)]


def stage_init(P, io, G):
    nc = P.nc
    G["identf"] = P.gsb([128, 128], F32, "identf")
    G["identb"] = P.gsb([128, 128], BF16, "identb")
    G["onesb"] = P.gsb([128, 128], BF16, "onesb")
    G["bones"] = P.gsb([128, 128], BF16, "bones")
    G["masks"] = P.gsb([128, 4, 128], BF16, "masks")
    G["perm"] = P.gsb([128, 128], BF16, "perm")
    G["rmask"] = P.gsb([128, 256], F32, "rmask")
    G["vec"] = P.gsb([128, NV, 8], F32, "vec")
    G["modv"] = P.gsb([128, 2, 6, 8, 2], F32, "modv")
    G["gg"] = P.gsb([128, 2, 2, 8, 2], F32, "gg")
    with P.phase("init"):
        P.dma("sync", G["identf"][:], io["c_ident"], writes=["identf"], sem="identf")
        P.dma("sync", G["rmask"][:], io["c_rmask"], writes=["rmask"], sem="rmask")
        P.dma("sync", G["vec"][:], io["vecs"], writes=["vec"], sem="vec")
        P.dma("gpsimd", G["identb"][:], io["c_ident"], writes=["identb"], sem="identb")
        P.dma("gpsimd", G["onesb"][:], io["c_ones"], writes=["onesb"], sem="onesb")
        P.dma("gpsimd", G["bones"][:], io["c_bones"], writes=["bones"], sem="bones")
        P.dma("gpsimd", G["masks"][:], io["c_masks"], writes=["masks"], sem="masks")
        P.dma("gpsimd", G["perm"][:], io["c_perm"], writes=["perm"], sem="perm")
        vec = G["vec"]
        P.ts("vector", vec[:, 17, :], vec[:, 16, :], -1.0, 1.0, ALU.mult, ALU.add, ["vec"], ["vec"])
        sv = P.sb([128, 8, 2], F32)
        svs = P.sb([128, 8, 2], F32)
        P.dma("sync", sv[:], io["cvec"], writes=["sv"], sem="sv")
        P.act(svs[:], sv[:], AF.Silu, ["sv"], ["svs"])
        brow = P.sb([2, 2 * 6144], F32)
        row = P.sb([2, 2 * 6144], F32)
        P.dma("sync", brow[:], io["b_mod"].rearrange("l n -> (l n)").partition_broadcast(2), writes=["brow"], sem="brow")
        wt = [P.sb([128, 8, 512], F32) for _ in range(2)]
        psr = [P.ps([128, 512], F32) for _ in range(2)]
        pst = P.ps([128, 512], F32)
        k = 0
        for l in range(2):
            for nb in range(12):
                b = k % 2
                k += 1
                P.dma("sync", wt[b][:], fm(io["w_mod"][l, :, nb * 512:(nb + 1) * 512]), writes=[f"wt{b}"], sem=f"wt{b}")
                for c in range(8):
                    P.mm(psr[b][0:2, :], svs[:, c, :], wt[b][:, c, :], c == 0, c == 7, ["svs", f"wt{b}"], [f"psr{b}"])
                o = l * 6144 + nb * 512
                P.tt("vector", row[:, o:o + 512], psr[b][0:2, :], brow[:, o:o + 512], ALU.add, [f"psr{b}", "brow"], ["row"])
        for l in range(2):
            for blk in range(48):
                o = l * 6144 + blk * 128
                P.tr(pst[:, l * 96 + blk * 2:l * 96 + blk * 2 + 2], row[0:2, o:o + 128], G["identf"][0:2, 0:2], ["row", "identf"], ["pst"])
        P.cp("vector", G["modv"][:].rearrange("p l m c j -> p (l m c j)"), pst[:, 0:192], ["pst"], ["modv"])
        modv, gg = G["modv"], G["gg"]
        for l in range(2):
            for kind in range(2):
                sc = modv[:, l, 1 + 3 * kind, :, :]
                nv = vec[:, (0 if kind == 0 else 2) + l, :].unsqueeze(2).broadcast_to([128, 8, 2])
                P.ts("vector", gg[:, l, kind, :, :], sc, 1.0, None, ALU.add, None, ["modv"], ["gg"])
                P.tt("vector", gg[:, l, kind, :, :], gg[:, l, kind, :, :], nv, ALU.mult, ["gg", "vec"], ["gg"])


def mod_scalars(G, l, kind, isctx):
    j = 1 if isctx else 0
    gains = [G["gg"][:, l, kind, c, j:j + 1] for c in range(8)]
    shifts = [G["modv"][:, l, 3 * kind, c, j:j + 1] for c in range(8)]
    gates = [G["modv"][:, l, 3 * kind + 2, c, j:j + 1] for c in range(8)]
    return gains, shifts, gates


def stage_norm(P, io, G, name, src, tiles, gains_fn, shifts_fn, dst_fn, out_dtype):
    with P.phase(name):
        xt = [P.sb([128, 8, 512], F32) for _ in range(2)]
        sq = P.sb([128, 8, 512], BF16)
        lnv = P.sb([128, 512], F32)
        rstd = P.sb([128, 512], F32)
        tmp = [P.sb([128, 512], F32) for _ in range(2)]
        ho = [P.sb([128, 8, 512], out_dtype) for _ in range(2)]
        ps = [P.ps([128, 512], F32) for _ in range(2)]

        def load(i):
            c0, tw, _ = tiles[i]
            b = i % 2
            P.dma("sync", xt[b][:, :, :tw], fm(src[:, c0:c0 + tw]), writes=[f"xt{b}"], sem=f"xt{b}")

        load(0)
        for i, (c0, tw, isctx) in enumerate(tiles):
            b = i % 2
            if i + 1 < len(tiles):
                load(i + 1)
            gains = gains_fn(isctx)
            shifts = shifts_fn(isctx)
            P.act(sq[:, :, :tw], xt[b][:, :, :tw], AF.Square, [f"xt{b}"], ["sq"])
            for c in range(8):
                P.mm(ps[b][:, :tw], G["onesb"][:], sq[:, c, :tw], c == 0, c == 7, ["sq", "onesb"], [f"ps{b}"])
            P.act(lnv[:, :tw], ps[b][:, :tw], AF.Ln, [f"ps{b}"], ["lnv"], bias=1e-6, scale=1.0 / D)
            P.act(rstd[:, :tw], lnv[:, :tw], AF.Exp, ["lnv"], ["rstd"], scale=-0.5)
            for c in range(8):
                if shifts is None:
                    P.stt(ho[b][:, c, :tw], xt[b][:, c, :tw], gains[c], rstd[:, :tw], ALU.mult, ALU.mult,
                          [f"xt{b}", "rstd", "vec", "gg"], [f"ho{b}"])
                else:
                    t = tmp[c % 2]
                    P.stt(t[:, :tw], xt[b][:, c, :tw], gains[c], rstd[:, :tw], ALU.mult, ALU.mult,
                          [f"xt{b}", "rstd", "vec", "gg"], [f"tmp{c % 2}"])
                    P.act(ho[b][:, c, :tw], t[:, :tw], AF.Identity, [f"tmp{c % 2}", "modv"], [f"ho{b}"], bias=shifts[c])
            P.dma("sync", dst_fn(c0, tw, isctx), ho[b][:, :, :tw], reads=[f"ho{b}"], writes=[("dst", i)], sem=f"ho{b}")


def stage_mlp(P, io, G, l, tiles, xa, hb):
    for half in range(2):
        with P.phase(f"mlp{l}{half}"):
            w1 = P.sb([128, 8, 2048], BF16)
            w2 = P.sb([128, 16, 1024], BF16)
            for q in range(2):
                P.dma("gpsimd", w1[:, :, q * 1024:(q + 1) * 1024],
                      fm(io["mlp_w1"][l, :, half * 2048 + q * 1024: half * 2048 + (q + 1) * 1024]), writes=["w1"], sem=f"w1{q}")
                P.dma("gpsimd", w2[:, q * 8:(q + 1) * 8, :],
                      io["mlp_w2"][l, half * 2048 + q * 1024: half * 2048 + (q + 1) * 1024, :].rearrange("(f p) n -> p f n", p=128),
                      writes=["w2"], sem=f"w2{q}")
            xt = [P.sb([128, 8, 512], F32) for _ in range(2)]
            ht = [P.sb([128, 8, 512], BF16) for _ in range(2)]
            h1 = P.sb([128, 16, 512], BF16)
            r1 = [P.sb([128, 512], F32) for _ in range(2)]
            ps = [P.ps([128, 512], F32) for _ in range(4)]

            def load(i):
                c0, tw, _ = tiles[i]
                b = i % 2
                P.dma("sync", ht[b][:, :, :tw], fm(hb[:, c0:c0 + tw]), writes=[f"ht{b}"], sem=f"ht{b}")
                P.dma("sync", xt[b][:, :, :tw], fm(xa[:, c0:c0 + tw]), reads=[("xa", i)], writes=[f"xt{b}"], sem=f"xt{b}")

            load(0)
            for i, (c0, tw, isctx) in enumerate(tiles):
                b = i % 2
                if i + 1 < len(tiles):
                    load(i + 1)
                _, _, gates = mod_scalars(G, l, 1, isctx)
                for fc in range(16):
                    pb = fc % 2
                    for c in range(8):
                        P.mm(ps[pb][:, :tw], w1[:, c, fc * 128:(fc + 1) * 128], ht[b][:, c, :tw], c == 0, c == 7,
                             ["w1", f"ht{b}"], [f"ps{pb}"])
                    P.act(r1[pb][:, :tw], ps[pb][:, :tw], AF.Relu, [f"ps{pb}"], [f"r1{pb}"])
                    P.tt("gpsimd", h1[:, fc, :tw], r1[pb][:, :tw], r1[pb][:, :tw], ALU.mult, [f"r1{pb}"], [("h1", fc)])
                for oc in range(8):
                    pb = 2 + oc % 2
                    for fc in range(16):
                        P.mm(ps[pb][:, :tw], w2[:, fc, oc * 128:(oc + 1) * 128], h1[:, fc, :tw], fc == 0, fc == 15,
                             ["w2", ("h1", fc)], [f"ps{pb}"])
                    P.stt(xt[b][:, oc, :tw], ps[pb][:, :tw], gates[oc], xt[b][:, oc, :tw], ALU.mult, ALU.add,
                          [f"ps{pb}", f"xt{b}", "modv"], [f"xt{b}"])
                P.dma("sync", fm(xa[:, c0:c0 + tw]), xt[b][:, :, :tw], reads=[f"xt{b}"], writes=[("xa", i)], sem=f"xt{b}")


RW_ORDER1 = [(True, 0)] + [(False, i) for i in range(16)]
RW_ORDER2 = [(True, 0)] + [(False, i) for i in range(15, -1, -1)]


def rw_scratch(io):
    S = {}
    S["yp"] = io.scratch("rw_yp", [17, 8, 128, 256], F32)
    S["sadd"] = io.scratch("rw_sadd", [17, 8, 128, 256], F32)
    S["vst"] = io.scratch("rw_vst", [17, 8, 128, 256], F32)
    S["gst"] = io.scratch("rw_gst", [17, 8, 128, 256], F32)
    S["gyb"] = io.scratch("rw_gyb", [17, 8, 128, 512], BF16)
    S["gsb"] = io.scratch("rw_gsb", [17, 8, 128, 512], BF16)
    S["gamb"] = io.scratch("rw_gamb", [17, 128, 32], F32)
    S["bon"] = io.scratch("rw_bon", [17, 128, 32], F32)
    return S


def stage_rwkv1(P, io, G, hp, S, dbg=None):
    vec, masks, identb, identf, bones, onesb, rmask = (G[k] for k in ("vec", "masks", "identb", "identf", "bones", "onesb", "rmask"))
    with P.phase("rwkv1"):
        wr = P.sb([128, 8, 1024], BF16)
        wk = P.sb([128, 8, 1024], BF16)
        wv = P.sb([128, 8, 1024], BF16)
        for w, nm in ((wr, "rwkv_wr"), (wk, "rwkv_wk"), (wv, "rwkv_wv")):
            P.dma("gpsimd", w[:], fm(io[nm]), writes=[nm], sem=nm)
        lw1 = P.sb([128, 8, 128], BF16)
        la1 = P.sb([128, 8, 128], BF16)
        g1 = P.sb([128, 8, 128], BF16)
        for d in range(2):
            P.dma("gpsimd", lw1[:, :, d * 64:(d + 1) * 64], io["rwkv_w1"][d].rearrange("(c p) j -> p c j", p=128), writes=["lw1"], sem=f"lw1{d}")
            P.dma("gpsimd", la1[:, :, d * 64:(d + 1) * 64], io["rwkv_a1"][d].rearrange("(c p) j -> p c j", p=128), writes=["la1"], sem=f"la1{d}")
        P.dma("gpsimd", g1[:], io["rwkv_g1"].rearrange("(c p) j -> p c j", p=128), writes=["g1"], sem="g1")
        w2s = P.sb([128, 1024], BF16)
        a2s = P.sb([128, 1024], BF16)
        g2 = P.sb([128, 1024], BF16)
        P.dma("gpsimd", w2s[:], io["rwkv_w2"].rearrange("d j f -> (d j) f"), writes=["w2s"], sem="w2s")
        P.dma("gpsimd", a2s[:], io["rwkv_a2"].rearrange("d j f -> (d j) f"), writes=["a2s"], sem="a2s")
        P.dma("gpsimd", g2[:], io["rwkv_g2"], writes=["g2"], sem="g2")

        hh = P.sb([128, 8, 384], F32)
        xx = P.sb([128, 8, 256], F32)
        xr = P.sb([128, 8, 256], BF16)
        xk = P.sb([128, 8, 256], BF16)
        xv = P.sb([128, 8, 256], BF16)
        xrot = P.sb([128, 8, 256], BF16)
        lwt = P.sb([128, 256], BF16)
        lat = P.sb([128, 256], BF16)
        sg = P.sb([128, 256], BF16)
        f32t = {}
        for nm in ("r", "k", "sw0", "sw1", "ag0", "ag1", "kq", "lnv", "rs", "kkn", "fac", "kd0", "kd1", "b0", "b1",
                   "L", "Lx", "Lb", "E1", "E2", "E3", "ks"):
            f32t[nm] = P.sb([128, 256], F32, "t_" + nm)
        sqb = P.sb([128, 256], BF16)
        RK = P.sb([128, 4, 2, 64], BF16)
        VTbd = P.sb([128, 4, 128], F32)
        GTbd = P.sb([128, 4, 128], F32)
        Vf = P.sb([128, 4, 64], F32)
        Gf = P.sb([128, 4, 64], F32)
        YPs = P.sb([128, 4, 64], F32)
        SAs = P.sb([128, 4, 64], F32)
        gamb_t = P.sb([128, 8, 4], F32)
        bon_t = P.sb([128, 8, 4], F32)
        Sf = P.sb([128, 8, 64], BF16)
        ARq = [[P.sb([128, 4, 2, 128], BF16, f"AR{q}{d}") for d in range(2)] for q in range(2)]
        KTq = [[P.sb([128, 4, 128], BF16, f"KT{q}{d}") for d in range(2)] for q in range(2)]
        BTq = [[P.sb([128, 4, 128], BF16, f"BT{q}{d}") for d in range(2)] for q in range(2)]
        Vbq = [P.sb([128, 4, 64], BF16, f"Vb{q}") for q in range(2)]
        gamq = [[P.sb([128, 4], F32, f"gam{q}{d}") for d in range(2)] for q in range(2)]
        inv = []
        for d in range(2):
            st = {}
            for nm, shp in (("Atok", [128, 4, 128]), ("Btok", [128, 4, 128]), ("MQ", [128, 4, 256]), ("MWa", [128, 4, 2, 128]),
                            ("MWb", [128, 4, 2, 128]), ("MTa", [128, 4, 128]), ("MTb", [128, 4, 128])):
                st[nm] = P.sb(shp, BF16, f"i{d}_{nm}")
            inv.append(st)
        fin = []
        for q in range(2):
            row = []
            for d in range(2):
                st = {}
                for nm, shp in (("Ktok", [128, 4, 128]), ("NP", [128, 4, 256]), ("XW", [128, 4, 256]), ("NVb", [128, 4, 64]),
                                ("GY", [128, 4, 128]), ("GS", [128, 4, 128])):
                    st[nm] = P.sb(shp, BF16, f"f{q}{d}_{nm}")
                row.append(st)
            fin.append(row)
        ppt = [P.ps([128, 512], F32) for _ in range(2)]
        pp = [t_[:, 0:256] for t_ in ppt]
        pf = P.ps([128, 512], F32)
        pa = [P.ps([128, 512], F32) for _ in range(3)]
        pb = [P.ps([128, 512], F32) for _ in range(2)]
        cnt = {"pp": 0, "pa": 0, "pb": 0}
        nmod = {"pp": 2, "pa": 3, "pb": 2}

        def nxt(kind):
            i = cnt[kind] % nmod[kind]
            cnt[kind] += 1
            return i

        def a_step(mm_fn, evac_fn):
            for half in range(2):
                j = nxt("pa")
                for u in (2 * half, 2 * half + 1):
                    mm_fn(u, pa[j][:, (u % 2) * 256:(u % 2 + 1) * 256], f"pa{j}")
                evac_fn(slice(2 * half, 2 * half + 2), pa[j][:].rearrange("p (u a x) -> p u a x", a=2, x=128), f"pa{j}")

        for q in range(2):
            for d in range(2):
                P.memset("gpsimd", ARq[q][d][:], 0.0, [f"AR{q}{d}"])
                P.memset("gpsimd", KTq[q][d][:], 0.0, [f"KT{q}{d}"])
                P.memset("gpsimd", BTq[q][d][:], 0.0, [f"BT{q}{d}"])
        P.memset("gpsimd", RK[:], 0.0, ["RK"])
        P.memset("gpsimd", VTbd[:], 0.0, ["VTbd"])
        P.memset("gpsimd", GTbd[:], 0.0, ["GTbd"])
        P.memset("gpsimd", Sf[:], 0.0, [("Sf", p) for p in range(8)])

        def v3(ap):
            return ap.rearrange("p (u s) -> p u s", s=64)

        def u128(ap):
            return ap.rearrange("p (u x) -> p u x", x=128)

        def load_hh(ti):
            isctx, idx = RW_ORDER1[ti]
            off = 4288 if isctx else 64 + 256 * idx
            P.dma("sync", hh[:], fm(hp[:, off - 64: off + 320]), writes=["hh"], sem="hh")

        def proj8(w_cols_fn, xb, bn, extra_r):
            i = nxt("pp")
            for c in range(8):
                P.mm(pp[i], w_cols_fn(c), xb[:, c, :], c == 0, c == 7, [(bn, c)] + extra_r, [f"pp{i}"])
            return i

        def tprep(ti):
            isctx, idx = RW_ORDER1[ti]
            hc = hh[:, :, 64:320]
            XXW = [("xx", c) for c in range(8)]
            if not isctx:
                h4 = hh[:, :, 64:320].rearrange("p c (r w) -> p c r w", w=64)
                x4 = xx[:].rearrange("p c (r w) -> p c r w", w=64)
                P.tt("vector", x4[:, 0:2, :, 1:64], h4[:, 0:2, :, 0:63], h4[:, 0:2, :, 1:64], ALU.subtract, ["hh"], XXW[0:2])
                P.ts("gpsimd", x4[:, 0:2, :, 0:1], h4[:, 0:2, :, 0:1], -1.0, 0.0, ALU.mult, ALU.add, ["hh"], [("xxe", 0)])
                P.tt("vector", x4[:, 2:4, :, 0:63], h4[:, 2:4, :, 1:64], h4[:, 2:4, :, 0:63], ALU.subtract, ["hh"], XXW[2:4])
                P.ts("gpsimd", x4[:, 2:4, :, 63:64], h4[:, 2:4, :, 63:64], -1.0, 0.0, ALU.mult, ALU.add, ["hh"], [("xxe", 1)])
                P.tt("gpsimd", xx[:, 4:6, :], hh[:, 4:6, 0:256], hh[:, 4:6, 64:320], ALU.subtract, ["hh"], XXW[4:6])
                P.tt("gpsimd", xx[:, 6:8, :], hh[:, 6:8, 128:384], hh[:, 6:8, 64:320], ALU.subtract, ["hh"], XXW[6:8])
            else:
                P.tt("vector", xx[:, 0:4, :], hh[:, 0:4, 63:319], hh[:, 0:4, 64:320], ALU.subtract, ["hh"], XXW[0:4] + [("xxe", 0)])
                P.tt("gpsimd", xx[:, 4:8, :], hh[:, 4:8, 65:321], hh[:, 4:8, 64:320], ALU.subtract, ["hh"], XXW[4:8] + [("xxe", 1)])
            yield

            def mk_xj(j, buf, bn):
                for c in range(8):
                    P.stt(buf[:, c, :], xx[:, c, :], vec[:, 5 + j, c:c + 1], hc[:, c, :], ALU.mult, ALU.add,
                          [("xx", c), ("xxe", 0), ("xxe", 1), "hh", "vec"], [(bn, c)])

            mk_xj(1, xrot, "xrot")
            yield
            i = proj8(lambda c: lw1[:, c, :], xrot, "xrot", ["lw1"])
            P.act(lwt[:], pp[i], AF.Tanh, [f"pp{i}"], ["lwt"])
            yield
            mk_xj(4, xrot, "xrot")
            yield
            i = proj8(lambda c: la1[:, c, :], xrot, "xrot", ["la1"])
            P.cp("scalar", lat[:], pp[i], [f"pp{i}"], ["lat"])
            yield
            mk_xj(5, xrot, "xrot")
            yield
            i = proj8(lambda c: g1[:, c, :], xrot, "xrot", ["g1"])
            P.act(sg[:], pp[i], AF.Sigmoid, [f"pp{i}"], ["sg"])
            yield
            mk_xj(0, xr, "xr")
            yield
            mk_xj(2, xk, "xk")
            yield
            mk_xj(3, xv, "xv")
            if ti + 1 < len(RW_ORDER1):
                load_hh(ti + 1)
            yield

        def prep(ti, oc, q):
            isctx, idx = RW_ORDER1[ti]
            tg = 16 if isctx else idx
            cs = slice(oc * 128, (oc + 1) * 128)
            t = f32t
            AR, KT, BT, Vb, gam = ARq[q], KTq[q], BTq[q], Vbq[q], gamq[q]
            i = proj8(lambda c: wr[:, c, cs], xr, "xr", ["rwkv_wr"])
            P.cp("scalar", t["r"][:], pp[i], [f"pp{i}"], ["r"])
            i = proj8(lambda c: wk[:, c, cs], xk, "xk", ["rwkv_wk"])
            P.cp("scalar", t["k"][:], pp[i], [f"pp{i}"], ["k"])
            i = proj8(lambda c: wv[:, c, cs], xv, "xv", ["rwkv_wv"])
            vt4 = VTbd[:].rearrange("p u (h s) -> p u h s", h=2)
            for h2 in range(2):
                sl = slice(h2 * 64, (h2 + 1) * 64)
                P.cp("scalar", vt4[sl, :, h2, :], v3(pp[i][sl, :]), [f"pp{i}"], ["VTbd"])
            i = nxt("pp")
            P.mm(pp[i], g2[:, cs], sg[:], True, True, ["g2", "sg"], [f"pp{i}"])
            gt4 = GTbd[:].rearrange("p u (h s) -> p u h s", h=2)
            for h2 in range(2):
                sl = slice(h2 * 64, (h2 + 1) * 64)
                P.cp("scalar", gt4[sl, :, h2, :], v3(pp[i][sl, :]), [f"pp{i}"], ["GTbd"])
            yield
            j = nxt("pb")
            for u in range(4):
                P.tr(pb[j][:, u * 128:(u + 1) * 128], VTbd[:, u, :], identf[:], ["VTbd", "identf"], [f"pb{j}"])
            pv = u128(pb[j][:])
            for h2 in range(2):
                sl = slice(h2 * 64, (h2 + 1) * 64)
                P.cp("scalar", Vf[sl, :, :], pv[sl, :, h2 * 64:(h2 + 1) * 64], [f"pb{j}"], ["Vf"])
            P.cp("gpsimd", Vb[:], Vf[:], ["Vf"], [f"Vb{q}"])
            P.dma("sync", S["vst"][tg, oc].rearrange("p (u s) -> p u s", s=64), Vf[:], reads=["Vf"], writes=[("vst", tg, oc)], sem="Vf")
            j = nxt("pb")
            for u in range(4):
                P.tr(pb[j][:, u * 128:(u + 1) * 128], GTbd[:, u, :], identf[:], ["GTbd", "identf"], [f"pb{j}"])
            pv = u128(pb[j][:])
            for h2 in range(2):
                sl = slice(h2 * 64, (h2 + 1) * 64)
                P.cp("scalar", Gf[sl, :, :], pv[sl, :, h2 * 64:(h2 + 1) * 64], [f"pb{j}"], ["Gf"])
            P.dma("sync", S["gst"][tg, oc].rearrange("p (u s) -> p u s", s=64), Gf[:], reads=["Gf"], writes=[("gst", tg, oc)], sem="Gf")
            yield
            for d in range(2):
                dl = slice(d * 64, (d + 1) * 64)
                i = nxt("pp")
                P.mm(pp[i], w2s[dl, cs], lwt[dl, :], True, True, ["w2s", "lwt"], [f"pp{i}"])
                P.act(t[f"sw{d}"][:], pp[i], AF.Sigmoid, [f"pp{i}", "vec"], [f"sw{d}"], bias=vec[:, 11 + d, oc:oc + 1])
                i = nxt("pp")
                P.mm(pp[i], a2s[dl, cs], lat[dl, :], True, True, ["a2s", "lat"], [f"pp{i}"])
                P.act(t[f"ag{d}"][:], pp[i], AF.Sigmoid, [f"pp{i}", "vec"], [f"ag{d}"], bias=vec[:, 13 + d, oc:oc + 1])
            yield
            P.ts("vector", t["kq"][:], t["k"][:], vec[:, 15, oc:oc + 1], None, ALU.mult, None, ["k", "vec"], ["kq"])
            P.act(sqb[:], t["kq"][:], AF.Square, ["kq"], ["sqb"])
            i = nxt("pp")
            P.mm(pp[i], bones[:], sqb[:], True, True, ["bones", "sqb"], [f"pp{i}"])
            P.act(t["lnv"][:], pp[i], AF.Ln, [f"pp{i}"], ["lnv"], bias=1e-12)
            P.act(t["rs"][:], t["lnv"][:], AF.Exp, ["lnv"], ["rs"], scale=-0.5)
            P.tt("gpsimd", t["kkn"][:], t["kq"][:], t["rs"][:], ALU.mult, ["kq", "rs"], ["kkn"])
            for d in range(2):
                sw, ag, kd, bb = t[f"sw{d}"], t[f"ag{d}"], t[f"kd{d}"], t[f"b{d}"]
                P.ts("gpsimd", t["fac"][:], ag[:], vec[:, 16, oc:oc + 1], vec[:, 17, oc:oc + 1], ALU.mult, ALU.add, [f"ag{d}", "vec"], ["fac"])
                P.tt("gpsimd", kd[:], t["k"][:], t["fac"][:], ALU.mult, ["k", "fac"], [f"kd{d}"])
                P.tt("gpsimd", bb[:], t["kkn"][:], ag[:], ALU.mult, ["kkn", f"ag{d}"], [f"b{d}"])
                P.op("vector", lambda e, sw=sw: e.tensor_tensor_scan(out=t["L"][:], data0=rmask[:], data1=sw[:], initial=0.0,
                                                                      op0=ALU.mult, op1=ALU.add), [f"sw{d}", "rmask"], ["L"])
                L3 = v3(t["L"][:])
                if d == 0:
                    P.tt("gpsimd", t["Lx"][:], t["L"][:], sw[:], ALU.subtract, ["L", f"sw{d}"], ["Lx"])
                    Li, Lin = t["L"], "L"
                else:
                    P.tt("gpsimd", v3(t["Lx"][:]), L3[:, :, 63:64].broadcast_to([128, 4, 64]), L3, ALU.subtract, ["L"], ["Lx"])
                    P.tt("gpsimd", t["Lb"][:], t["Lx"][:], sw[:], ALU.add, ["Lx", f"sw{d}"], ["Lb"])
                    Li, Lin = t["Lb"], "Lb"
                P.act(t["E1"][:], Li[:], AF.Exp, [Lin], ["E1"], scale=-C0)
                P.act(t["E3"][:], Li[:], AF.Exp, [Lin], ["E3"], scale=C0)
                P.act(t["E2"][:], t["Lx"][:], AF.Exp, ["Lx"], ["E2"], scale=-C0)
                ar5 = AR[d][:].rearrange("p u a (h s) -> p u a h s", h=2)
                kt4 = KT[d][:].rearrange("p u (h s) -> p u h s", h=2)
                bt4 = BT[d][:].rearrange("p u (h s) -> p u h s", h=2)
                for h2 in range(2):
                    sl = slice(h2 * 64, (h2 + 1) * 64)
                    P.stt(ar5[sl, :, 0, h2, :], v3(t["kkn"][sl, :]), -1.0, v3(t["E2"][sl, :]), ALU.mult, ALU.mult, ["kkn", "E2"], [f"AR{q}{d}"])
                    P.tt("gpsimd", ar5[sl, :, 1, h2, :], v3(t["r"][sl, :]), v3(t["E1"][sl, :]), ALU.mult, ["r", "E1"], [f"AR{q}{d}"])
                    P.tt("vector", kt4[sl, :, h2, :], v3(kd[sl, :]), v3(t["E3"][sl, :]), ALU.mult, [f"kd{d}", "E3"], [f"KT{q}{d}"])
                    P.tt("gpsimd", bt4[sl, :, h2, :], v3(bb[sl, :]), v3(t["E3"][sl, :]), ALU.mult, [f"b{d}", "E3"], [f"BT{q}{d}"])
                E13 = v3(t["E1"][:])
                gsrc = E13[:, :, 63] if d == 0 else E13[:, :, 0]
                P.cp("vector", gam[d][:], gsrc, ["E1"], [f"gam{q}{d}"])
                if d == 1:
                    P.cp("gpsimd", gamb_t[:, oc, :], gam[1][:], [f"gam{q}1"], ["gamb_t"])
                yield
            P.tt("gpsimd", t["ks"][:], t["kd0"][:], t["kd1"][:], ALU.add, ["kd0", "kd1"], ["ks"])
            for h2 in range(2):
                sl = slice(h2 * 64, (h2 + 1) * 64)
                P.stt(RK[sl, :, h2, :], v3(t["r"][sl, :]), vec[sl, 18, oc:oc + 1], v3(t["ks"][sl, :]), ALU.mult, ALU.mult, ["r", "ks", "vec"], ["RK"])
            i = nxt("pp")
            for u in range(4):
                P.mm(pp[i][:, u:u + 1], RK[:, u, :, :].rearrange("p h s -> p (h s)"), onesb[:, 0:1], True, True, ["RK", "onesb"], [f"pp{i}"])
            P.cp("scalar", bon_t[:, oc, :], pp[i][:, 0:4], [f"pp{i}"], ["bon_t"])
            if oc == 7:
                P.dma("sync", S["gamb"][tg], gamb_t[:].rearrange("p a b -> p (a b)"), reads=["gamb_t"], writes=[("gamb", tg)], sem="gamb_t")
                P.dma("sync", S["bon"][tg], bon_t[:].rearrange("p a b -> p (a b)"), reads=["bon_t"], writes=[("bon", tg)], sem="bon_t")
            yield

        def chain(ti, oc, q, d):
            AR, KT, BT, Vb = ARq[q][d], KTq[q][d], BTq[q][d], Vbq[q]
            ARn, KTn, BTn, Vbn = f"AR{q}{d}", f"KT{q}{d}", f"BT{q}{d}", f"Vb{q}"
            iv, fn = inv[d], fin[q][d]
            IR = lambda nm: f"i{d}_{nm}"
            FR = lambda nm: f"f{q}{d}_{nm}"
            mS, mC = (0, 2) if d == 0 else (2, 0)
            mSI = masks[:, mS:mS + 2, :].rearrange("p a b -> p (a b)").unsqueeze(1).broadcast_to([128, 4, 256])
            mCb = masks[:, mC, :].unsqueeze(1).broadcast_to([128, 4, 128])
            idb = identb[:].unsqueeze(1).broadcast_to([128, 4, 128])
            for src, srcn, dst, dstn in ((AR[:, :, 0, :], ARn, iv["Atok"], IR("Atok")), (BT[:], BTn, iv["Btok"], IR("Btok")),
                                         (KT[:], KTn, fn["Ktok"], FR("Ktok"))):
                j = nxt("pb")
                pbt = pb[j][:].bitcast(BF16)
                for u in range(4):
                    P.tr(pbt[:, u * 128:(u + 1) * 128], src[:, u, :], identb[:], [srcn, "identb"], [f"pb{j}"])
                P.cp("scalar", dst[:].rearrange("p u x -> p (u x)"), pbt[:, 0:512], [f"pb{j}"], [dstn])
            mSI2 = masks[:, mS:mS + 2, :].unsqueeze(1).broadcast_to([128, 2, 2, 128])
            for lhs, lhsn, dst, dstn in ((BT, BTn, iv["MQ"], IR("MQ")), (KT, KTn, fn["NP"], FR("NP"))):
                a_step(lambda u, o, on: P.mm(o, lhs[:, u, :], AR[:, u, :, :].rearrange("p a x -> p (a x)"), True, True, [lhsn, ARn], [on]),
                       lambda us, pv, on: P.tt("vector", dst[:, us, :].rearrange("p u (a x) -> p u a x", a=2), pv, mSI2, ALU.mult, [on, "masks"], [dstn]))
            j = nxt("pb")
            for u in range(4):
                P.mm(pb[j][:, u * 128:(u + 1) * 128], AR[:, u, 0, :], BT[:, u, :], True, True, [ARn, BTn], [f"pb{j}"])
            cur, curn, nx, nxn = iv["MWa"], IR("MWa"), iv["MWb"], IR("MWb")
            P.tt("vector", cur[:, :, 0, :], u128(pb[j][:]), mCb, ALU.mult, [f"pb{j}", "masks"], [curn])
            yield
            j = nxt("pb")
            for u in range(4):
                P.mm(pb[j][:, u * 128:(u + 1) * 128], iv["MQ"][:, u, 0:128], cur[:, u, 0, :], True, True, [IR("MQ"), curn], [f"pb{j}"])
            P.cp("scalar", nx[:, :, 0, :], u128(pb[j][:]), [f"pb{j}"], [nxn])
            P.tt("gpsimd", nx[:, :, 1, :], cur[:, :, 0, :], idb, ALU.add, [curn, "identb"], [nxn])
            j = nxt("pb")
            for u in range(4):
                P.mm(pb[j][:, u * 128:(u + 1) * 128], cur[:, u, 0, :], iv["MQ"][:, u, 0:128], True, True, [IR("MQ"), curn], [f"pb{j}"])
            curT, curTn, nxT, nxTn = iv["MTa"], IR("MTa"), iv["MTb"], IR("MTb")
            P.cp("scalar", curT[:], u128(pb[j][:]), [f"pb{j}"], [curTn])
            cur, curn, nx, nxn = nx, nxn, cur, curn
            yield
            for lev in range(1, 5):
                def ev_lev(us, pv, on, cur=cur, curn=curn, nx=nx, nxn=nxn):
                    P.cp("scalar", nx[:, us, 0, :], pv[:, :, 0, :], [on], [nxn])
                    P.tt("vector", nx[:, us, 1, :], pv[:, :, 1, :], cur[:, us, 1, :], ALU.add, [on, curn], [nxn])
                a_step(lambda u, o, on, cur=cur, curn=curn, curT=curT, curTn=curTn:
                       P.mm(o, curT[:, u, :], cur[:, u, :, :].rearrange("p a x -> p (a x)"), True, True, [curTn, curn], [on]), ev_lev)
                j = nxt("pb")
                for u in range(4):
                    P.mm(pb[j][:, u * 128:(u + 1) * 128], cur[:, u, 0, :], curT[:, u, :], True, True, [curn, curTn], [f"pb{j}"])
                P.cp("scalar", nxT[:], u128(pb[j][:]), [f"pb{j}"], [nxTn])
                cur, curn, nx, nxn = nx, nxn, cur, curn
                curT, curTn, nxT, nxTn = nxT, nxTn, curT, curTn
                yield
            j = nxt("pb")
            for u in range(4):
                P.mm(pb[j][:, u * 128:(u + 1) * 128], curT[:, u, :], cur[:, u, 1, :], True, True, [curTn, curn], [f"pb{j}"])
            P.tt("vector", nx[:, :, 1, :], u128(pb[j][:]), cur[:, :, 1, :], ALU.add, [f"pb{j}", curn], [nxn])
            W6, W6n = nx, nxn
            j = nxt("pb")
            for u in range(4):
                P.mm(pb[j][:, u * 64:(u + 1) * 64], fn["NP"][:, u, 0:128], Vb[:, u, :], True, True, [FR("NP"), Vbn], [f"pb{j}"])
            P.cp("scalar", fn["NVb"][:].rearrange("p u x -> p (u x)"), pb[j][:, 0:256], [f"pb{j}"], [FR("NVb")])
            yield
            def mm_d(u, o, on):
                P.mm(o[:, 0:128], W6[:, u, 1, :], iv["MQ"][:, u, 128:256], True, True, [W6n, IR("MQ")], [on])
                P.mm(o[:, 128:256], W6[:, u, 1, :], iv["Btok"][:, u, :], True, True, [W6n, IR("Btok")], [on])
            a_step(mm_d, lambda us, pv, on: P.cp("scalar", fn["XW"][:, us, :].rearrange("p u (a x) -> p u a x", a=2), pv, [on], [FR("XW")]))
            yield
            def ev_f(us, pv, on):
                P.tt("vector", fn["GY"][:, us, :], pv[:, :, 0, :], AR[:, us, 1, :], ALU.add, [on, ARn], [FR("GY")])
                P.tt("vector", fn["GS"][:, us, :], pv[:, :, 1, :], idb[:, 0:2, :], ALU.add, [on, "identb"], [FR("GS")])
            a_step(lambda u, o, on: P.mm(o, iv["Atok"][:, u, :], fn["XW"][:, u, :], True, True, [IR("Atok"), FR("XW")], [on]), ev_f)
            yield

        def finish(ti, oc, q):
            isctx, idx = RW_ORDER1[ti]
            tg = 16 if isctx else idx
            sf, sb_ = fin[q]
            F0 = lambda nm: f"f{q}0_{nm}"
            F1 = lambda nm: f"f{q}1_{nm}"
            Vb, Vbn, gam = Vbq[q], f"Vb{q}", gamq[q]
            SFR = ("Sf", oc)
            for u in range(4):
                yo = pf[:, u * 64:(u + 1) * 64]
                P.mm(yo, sf["NP"][:, u, 128:256], Vb[:, u, :], True, False, [F0("NP"), Vbn], ["pf"])
                P.mm(yo, sf["XW"][:, u, 0:128], sf["NVb"][:, u, :], False, False, [F0("XW"), F0("NVb")], ["pf"])
                P.mm(yo, sb_["NP"][:, u, 128:256], Vb[:, u, :], False, False, [F1("NP"), Vbn], ["pf"])
                P.mm(yo, sb_["XW"][:, u, 0:128], sb_["NVb"][:, u, :], False, False, [F1("XW"), F1("NVb")], ["pf"])
                P.mm(yo, sf["GY"][:, u, :], Sf[:, oc, :], False, True, [F0("GY"), SFR], ["pf"])
                so = pf[:, 256:320]
                P.mm(so, sf["Ktok"][:, u, :], Vb[:, u, :], True, False, [F0("Ktok"), Vbn], ["pf"])
                P.mm(so, sf["XW"][:, u, 128:256], sf["NVb"][:, u, :], False, False, [F0("XW"), F0("NVb")], ["pf"])
                P.mm(so, sf["GS"][:, u, :], Sf[:, oc, :], False, True, [F0("GS"), SFR], ["pf"])
                P.ts("vector", Sf[:, oc, :], so, gam[0][:, u:u + 1], None, ALU.mult, None, ["pf", f"gam{q}0"], [SFR])
                yield
            P.cp("scalar", YPs[:].rearrange("p u x -> p (u x)"), pf[:, 0:256], ["pf"], ["YPs"])
            P.dma("sync", S["yp"][tg, oc], YPs[:].rearrange("p u x -> p (u x)"), reads=["YPs"], writes=[("yp", tg, oc)], sem="YPs")
            j = nxt("pb")
            for u in range(4):
                so = pb[j][:, u * 64:(u + 1) * 64]
                P.mm(so, sb_["Ktok"][:, u, :], Vb[:, u, :], True, False, [F1("Ktok"), Vbn], [f"pb{j}"])
                P.mm(so, sb_["XW"][:, u, 128:256], sb_["NVb"][:, u, :], False, True, [F1("XW"), F1("NVb")], [f"pb{j}"])
            P.cp("scalar", SAs[:].rearrange("p u x -> p (u x)"), pb[j][:, 0:256], [f"pb{j}"], ["SAs"])
            P.dma("sync", S["sadd"][tg, oc], SAs[:].rearrange("p u x -> p (u x)"), reads=["SAs"], writes=[("sadd", tg, oc)], sem="SAs")
            P.dma("sync", S["gyb"][tg, oc], sb_["GY"][:].rearrange("p u x -> p (u x)"), reads=[F1("GY")], writes=[("gyb", tg, oc)], sem=F1("GY"))
            P.dma("sync", S["gsb"][tg, oc], sb_["GS"][:].rearrange("p u x -> p (u x)"), reads=[F1("GS")], writes=[("gsb", tg, oc)], sem=F1("GS"))
            yield

        NT = len(RW_ORDER1)
        NJ = NT * 8
        done = {"prep": set(), "c0": set(), "c1": set(), "fin": set(), "tprep": set()}

        def stream_P():
            for ti in range(NT):
                yield ("tprep", ti, lambda ti=ti: (ti == 0 or ("prep", (ti - 1) * 8 + 7) in donef), lambda ti=ti: tprep(ti))
                for oc in range(8):
                    k = ti * 8 + oc
                    yield ("prep", k, lambda k=k: (k < 2 or (("c0", k - 2) in donef and ("c1", k - 2) in donef and ("fin", k - 2) in donef)),
                           lambda ti=ti, oc=oc, k=k: prep(ti, oc, k % 2))

        def stream_C(d):
            for k in range(NJ):
                ti, oc = divmod(k, 8)
                yield (f"c{d}", k, lambda k=k: (("prep", k) in donef and (k < 2 or ("fin", k - 2) in donef)),
                       lambda ti=ti, oc=oc, k=k: chain(ti, oc, k % 2, d))

        def stream_F():
            for k in range(NJ):
                ti, oc = divmod(k, 8)
                yield ("fin", k, lambda k=k: (("c0", k) in donef and ("c1", k) in donef),
                       lambda ti=ti, oc=oc, k=k: finish(ti, oc, k % 2))

        donef = set()
        load_hh(0)
        streams = [stream_P(), stream_C(0), stream_C(1), stream_F()]
        cur = [None] * 4
        pend = [None] * 4
        alive = [True] * 4
        while any(alive):
            progressed = False
            for si in range(4):
                if not alive[si]:
                    continue
                if cur[si] is None:
                    if pend[si] is None:
                        try:
                            pend[si] = next(streams[si])
                        except StopIteration:
                            alive[si] = False
                            continue
                    kind, k, ready, mk = pend[si]
                    if not ready():
                        continue
                    cur[si] = (kind, k, mk())
                    pend[si] = None
                kind, k, gen = cur[si]
                try:
                    next(gen)
                    progressed = True
                except StopIteration:
                    donef.add((kind, k))
                    cur[si] = None
                    progressed = True
            assert progressed or not any(alive), "scheduler stuck"


def stage_rwkv2(P, io, G, S, src, xa):
    vec, identb = G["vec"], G["identb"]
    GN_EPS = 64e-5
    with P.phase("rwkv2"):
        wo = P.sb([64, 16, 1024], BF16)
        P.dma("gpsimd", wo[:], io["rwkv_wo"].rearrange("(h v) f -> v h f", v=64), writes=["wo"], sem="wo")
        lnw = P.sb([128, 8, 64], F32)
        lnb = P.sb([128, 8, 64], F32)
        P.dma("sync", lnw[:], io["lnw_st"], writes=["lnw"], sem="lnw")
        P.dma("sync", lnb[:], io["lnb_st"], writes=["lnb"], sem="lnb")
        big = {}
        for nm in ("yp", "sadd", "vst", "gst"):
            big[nm] = [P.sb([128, 8, 256], F32, f"l_{nm}{b}") for b in range(2)]
        for nm in ("gyb", "gsb"):
            big[nm] = [P.sb([128, 8, 512], BF16, f"l_{nm}{b}") for b in range(2)]
        gamb = [P.sb([128, 8, 4], F32) for _ in range(2)]
        bon = [P.sb([128, 8, 4], F32) for _ in range(2)]
        xt = [P.sb([128, 8, 256], F32) for _ in range(2)]
        Sb = P.sb([128, 8, 64], BF16)
        ysb = P.sb([128, 8, 64], F32)
        ysq = P.sb([128, 8, 64], F32)
        tmpS = P.sb([128, 8, 64], F32)
        yn = P.sb([128, 8, 64], F32)
        bv = P.sb([128, 8, 64], F32)
        ob = P.sb([128, 8, 64], BF16)
        st = {nm: P.sb([128, 8], F32, "g_" + nm) for nm in ("s1", "s2", "mean", "msq", "var", "lnv", "rstd")}
        OT = P.sb([64, 16, 256], BF16)
        py = [P.ps([128, 512], F32) for _ in range(2)]
        pS = P.ps([128, 512], F32)
        ptr = P.ps([128, 1024], F32)
        pw = [P.ps([128, 512], F32) for _ in range(2)]
        P.memset("gpsimd", Sb[:], 0.0, ["Sb"])

        def load(k):
            isctx, idx = RW_ORDER2[k]
            tg = 16 if isctx else idx
            b = k % 2
            for nm in ("yp", "sadd", "vst", "gst", "gyb", "gsb"):
                P.dma("sync", big[nm][b][:], S[nm][tg].rearrange("o p x -> p o x"), writes=[f"{nm}{b}"], sem=f"{nm}{b}")
            P.dma("sync", gamb[b][:].rearrange("p a b -> p (a b)"), S["gamb"][tg], writes=[f"gamb{b}"], sem=f"gamb{b}")
            P.dma("sync", bon[b][:].rearrange("p a b -> p (a b)"), S["bon"][tg], writes=[f"bon{b}"], sem=f"bon{b}")
            c0 = T if isctx else idx * 256
            P.dma("sync", xt[b][:], fm(src[:, c0:c0 + 256]), writes=[f"xt{b}"], sem=f"xt{b}")

        load(0)
        for k, (isctx, idx) in enumerate(RW_ORDER2):
            b = k % 2
            if k + 1 < len(RW_ORDER2):
                load(k + 1)
            c0 = T if isctx else idx * 256
            _, _, gates = mod_scalars(G, 0, 0, isctx)
            bc = lambda ap: ap.unsqueeze(2).broadcast_to([128, 8, 64])
            def chain_part(u):
                us = slice(u * 64, (u + 1) * 64)
                q_ = u % 2
                for oc in range(8):
                    P.mm(py[q_][:, oc * 64:(oc + 1) * 64], big["gyb"][b][:, oc, u * 128:(u + 1) * 128], Sb[:, oc, :], True, True, [f"gyb{b}", "Sb"], [f"py{q_}"])
                for oc in range(8):
                    P.mm(pS[:, oc * 64:(oc + 1) * 64], big["gsb"][b][:, oc, u * 128:(u + 1) * 128], Sb[:, oc, :], True, True, [f"gsb{b}", "Sb"], ["pS"])
                pS3 = pS[:].rearrange("p (o v) -> p o v", v=64)
                P.tt("vector", tmpS[:], pS3, big["sadd"][b][:, :, us], ALU.add, ["pS", f"sadd{b}"], ["tmpS"])
                P.tt("vector", Sb[:], tmpS[:], bc(gamb[b][:, :, u]), ALU.mult, ["tmpS", f"gamb{b}"], ["Sb"])

            def read_part(u):
                us = slice(u * 64, (u + 1) * 64)
                q_ = u % 2
                py3 = py[q_][:].rearrange("p (o v) -> p o v", v=64)
                P.tt("vector", ysb[:], py3, big["yp"][b][:, :, us], ALU.add, [f"py{q_}", f"yp{b}"], ["ysb"])
                P.op("vector", lambda e: e.tensor_reduce(out=st["s1"][:], in_=ysb[:], axis=AX.X, op=ALU.add), ["ysb"], ["s1"])
                P.tt("gpsimd", ysq[:], ysb[:], ysb[:], ALU.mult, ["ysb"], ["ysq"])
                P.op("vector", lambda e: e.tensor_reduce(out=st["s2"][:], in_=ysq[:], axis=AX.X, op=ALU.add), ["ysq"], ["s2"])
                P.ts("vector", st["mean"][:], st["s1"][:], 1.0 / 64, None, ALU.mult, None, ["s1"], ["mean"])
                P.tt("vector", st["msq"][:], st["mean"][:], st["mean"][:], ALU.mult, ["mean"], ["msq"])
                P.stt(st["var"][:], st["s2"][:], 1.0 / 64, st["msq"][:], ALU.mult, ALU.subtract, ["s2", "msq"], ["var"])
                P.act(st["lnv"][:], st["var"][:], AF.Ln, ["var"], ["lnv"], bias=GN_EPS)
                P.act(st["rstd"][:], st["lnv"][:], AF.Exp, ["lnv"], ["rstd"], scale=-0.5)
                P.tt("gpsimd", yn[:], ysb[:], bc(st["mean"][:]), ALU.subtract, ["ysb", "mean"], ["yn"])
                P.tt("gpsimd", bv[:], big["vst"][b][:, :, us], bc(bon[b][:, :, u]), ALU.mult, [f"vst{b}", f"bon{b}"], ["bv"])
                P.tt("vector", yn[:], yn[:], bc(st["rstd"][:]), ALU.mult, ["yn", "rstd"], ["yn"])
                P.tt("gpsimd", yn[:], yn[:], lnw[:], ALU.mult, ["yn", "lnw"], ["yn"])
                P.tt("vector", yn[:], yn[:], lnb[:], ALU.add, ["yn", "lnb"], ["yn"])
                P.tt("gpsimd", yn[:], yn[:], bv[:], ALU.add, ["yn", "bv"], ["yn"])
                P.tt("vector", ob[:], yn[:], big["gst"][b][:, :, us], ALU.mult, ["yn", f"gst{b}"], ["ob"])
                ptb = ptr[:].bitcast(BF16)
                for oc in range(8):
                    P.tr(ptb[0:64, oc * 128:(oc + 1) * 128], ob[:, oc, :], identb[:], ["ob", "identb"], ["ptr"])
                P.cp("scalar", OT[:, :, us], ptb[0:64, 0:1024].rearrange("p (h t) -> p h t", t=64), ["ptr"], ["OT"])

            chain_part(3)
            for u in range(3, -1, -1):
                if u > 0:
                    chain_part(u - 1)
                read_part(u)
            for oc in range(8):
                j = oc % 2
                for h in range(16):
                    P.mm(pw[j][:, 0:256], wo[:, h, oc * 128:(oc + 1) * 128], OT[:, h, :], h == 0, h == 15, ["wo", "OT"], [f"pw{j}"])
                P.stt(xt[b][:, oc, :], pw[j][:, 0:256], gates[oc], xt[b][:, oc, :], ALU.mult, ALU.add, [f"pw{j}", f"xt{b}", "modv"], [f"xt{b}"])
            P.dma("sync", fm(xa[:, c0:c0 + 256]), xt[b][:], reads=[f"xt{b}"], writes=[("xa", k)], sem=f"xt{b}")


def stage_qkv(P, io, G, hb, qtd, Kz, VA):
    vec, bones, perm = G["vec"], G["bones"], G["perm"]
    with P.phase("qkv"):
        wq = P.sb([128, 8, 1024], BF16)
        wkd = P.sb([128, 8, 512], BF16)
        wv = P.sb([128, 8, 256], BF16)
        P.dma("gpsimd", wq[:], fm(io["attn_wq"]), writes=["wq"], sem="wq")
        P.dma("gpsimd", wkd[:], fm(io["attn_wkd"]), writes=["wkd"], sem="wkd")
        P.dma("gpsimd", wv[:], fm(io["attn_wv"]), writes=["wv"], sem="wv")
        ht = [P.sb([128, 8, 512], BF16) for _ in range(2)]
        cs = [P.sb([128, 512], F32) for _ in range(2)]
        sn = [P.sb([128, 512], F32) for _ in range(2)]
        NB = 2
        qf = [P.sb([128, 512], F32) for _ in range(NB)]
        sqb = [P.sb([128, 512], BF16) for _ in range(NB)]
        lnv = [P.sb([128, 512], F32) for _ in range(NB)]
        rstd = [P.sb([128, 512], F32) for _ in range(NB)]
        qh = [P.sb([128, 512], F32) for _ in range(NB)]
        qhb = [P.sb([128, 512], BF16) for _ in range(NB)]
        t1 = [P.sb([128, 512], F32) for _ in range(NB)]
        t2 = [P.sb([128, 512], F32) for _ in range(NB)]
        qst = [P.sb([128, 8, 512], BF16) for _ in range(2)]
        pp = [P.ps([128, 512], F32) for _ in range(6)]
        cnt = [0, 0]

        def nxt():
            cnt[0] += 1
            return cnt[0] % 6

        P.memset("gpsimd", VA[:], 0.0, ["VA0"])
        P.memset("gpsimd", VA[:].rearrange("p k (j x) -> p k j x", x=65)[:, :, 0:5, 64:65], 1.0, ["VA0"])
        P.memset("gpsimd", Kz[0][64:128, :, :], 0.0, ["Kz0z"])
        P.memset("gpsimd", Kz[1][0:64, :, :], 0.0, ["Kz1z"])
        tiles = ALL_TILES

        def load(i):
            c0, tw, isctx = tiles[i]
            b = i % 2
            P.dma("sync", ht[b][:, :, :tw], fm(hb[:, c0:c0 + tw]), writes=[f"ht{b}"], sem=f"ht{b}")
            if not isctx:
                P.dma("sync", cs[b][:, :tw], io["cosT"][:, c0:c0 + tw], writes=[f"cs{b}"], sem=f"cs{b}")
                P.dma("sync", sn[b][:, :tw], io["sinT"][:, c0:c0 + tw], writes=[f"sn{b}"], sem=f"sn{b}")

        def normrope(wcols, nscal, dsts, b, tw, isctx, wname, dres="dstqk"):
            cnt[1] += 1
            n = cnt[1] % NB
            i = nxt()
            for c in range(8):
                P.mm(pp[i][:, :tw], wcols(c), ht[b][:, c, :tw], c == 0, c == 7, [wname, f"ht{b}"], [f"pp{i}"])
            P.cp("scalar", qf[n][:, :tw], pp[i][:, :tw], [f"pp{i}"], [f"qf{n}"])
            P.act(sqb[n][:, :tw], qf[n][:, :tw], AF.Square, [f"qf{n}"], [f"sqb{n}"])
            yield
            i = nxt()
            P.mm(pp[i][:, :tw], bones[:], sqb[n][:, :tw], True, True, ["bones", f"sqb{n}"], [f"pp{i}"])
            P.act(lnv[n][:, :tw], pp[i][:, :tw], AF.Ln, [f"pp{i}"], [f"lnv{n}"], bias=1e-6, scale=1.0 / 64)
            P.act(rstd[n][:, :tw], lnv[n][:, :tw], AF.Exp, [f"lnv{n}"], [f"rstd{n}"], scale=-0.5)
            yield
            P.stt(qh[n][:, :tw], qf[n][:, :tw], nscal, rstd[n][:, :tw], ALU.mult, ALU.mult, [f"qf{n}", f"rstd{n}", "vec"], [f"qh{n}"])
            if isctx:
                for dst, sl in dsts:
                    P.cp("gpsimd", dst, qh[n][sl, :tw], [f"qh{n}"], [dres])
                return
            P.cp("gpsimd", qhb[n][:, :tw], qh[n][:, :tw], [f"qh{n}"], [f"qhb{n}"])
            yield
            i = nxt()
            P.mm(pp[i][:, :tw], perm[:], qhb[n][:, :tw], True, True, ["perm", f"qhb{n}"], [f"pp{i}"])
            P.tt("gpsimd", t1[n][:, :tw], qh[n][:, :tw], cs[b][:, :tw], ALU.mult, [f"qh{n}", f"cs{b}"], [f"t1{n}"])
            P.tt("vector", t2[n][:, :tw], pp[i][:, :tw], sn[b][:, :tw], ALU.mult, [f"pp{i}", f"sn{b}"], [f"t2{n}"])
            yield
            for dst, sl in dsts:
                P.tt("gpsimd", dst, t1[n][sl, :tw], t2[n][sl, :tw], ALU.add, [f"t1{n}", f"t2{n}"], [dres])

        ALLP = slice(0, 128)
        load(0)
        for i, (c0, tw, isctx) in enumerate(tiles):
            b = i % 2
            if i + 1 < len(tiles):
                load(i + 1)
            jobs = []
            if not isctx:
                for oc in range(8):
                    jobs.append(normrope(lambda c, oc=oc: wq[:, c, oc * 128:(oc + 1) * 128], vec[:, 19, oc:oc + 1], [(qst[b][:, oc, :tw], ALLP)], b, tw, False, "wq",
                                         dres=(f"qst{b}", oc)))
            for g in range(4):
                jobs.append(normrope(lambda c, g=g: wkd[:, c, g * 128:(g + 1) * 128], vec[:, 20, 0:1],
                                     [(Kz[0][0:64, g, c0:c0 + tw], slice(0, 64)), (Kz[1][64:128, g, c0:c0 + tw], slice(64, 128))], b, tw, isctx, "wkd"))

            def vjob():
                for sub in range(tw // 128):
                    kt = c0 // 128 + sub
                    j = nxt()
                    for c in range(8):
                        P.mm(pp[j][:, 0:256], ht[b][:, c, sub * 128:(sub + 1) * 128], wv[:, c, :], c == 0, c == 7, ["wv", f"ht{b}"], [f"pp{j}"])
                    P.cp("scalar", VA[:, kt, 65:325].rearrange("p (g x) -> p g x", x=65)[:, :, 0:64],
                         pp[j][:, 0:256].rearrange("p (g d) -> p g d", d=64), [f"pp{j}", "VA0"], [("VA", kt)])
                    yield

            jobs.append(vjob())
            active = []
            while jobs or active:
                while jobs and len(active) < 2:
                    active.append(jobs.pop(0))
                for gen in list(active):
                    try:
                        next(gen)
                    except StopIteration:
                        active.remove(gen)
            if not isctx:
                P.dma("sync", fm(qtd[:, c0:c0 + tw]), qst[b][:, :, :tw], reads=[(f"qst{b}", oc) for oc in range(8)], writes=[("qtd", i)], sem=f"qst{b}")


def stage_attn(P, io, G, qtd, Kz, VA, xa):
    with P.phase("attn"):
        wo = P.sb([128, 8, 1024], BF16)
        P.dma("gpsimd", wo[:], fm(io["attn_wo"]), writes=["wo"], sem="wo")
        sel = P.sb([128, 2, 128], F32)
        P.dma("sync", sel[:], io["c_sel"], writes=["sel"], sem="sel")
        PT = [P.sb([128, 1024], BF16) for _ in range(3)]
        osb = [P.sb([128, 512], F32) for _ in range(2)]
        rb = [P.sb([128, 512], F32) for _ in range(2)]
        xt = P.sb([128, 8, 512], F32)
        QB = [P.sb([128, 8, 512], BF16) for _ in range(2)]
        psS = [P.ps([128, 1024], F32) for _ in range(2)]
        psO = [P.ps([128, 512], F32) for _ in range(2)]
        psB = P.ps([128, 512], F32)
        pX = [P.ps([128, 512], F32) for _ in range(1)]
        _, _, gates = mod_scalars(G, 1, 0, False)
        for k in range(2):
            P.memset("gpsimd", osb[k][:], 0.0, [f"osb{k}"])
        def loadq(qb):
            P.dma("sync", QB[qb % 2][:], fm(qtd[:, qb * 512:(qb + 1) * 512]), writes=[("QT", h, qb) for h in range(16)], sem=f"QB{qb % 2}")

        loadq(0)
        for qb in range(8):
            qsl = slice(qb * 512, (qb + 1) * 512)
            QT = QB[qb % 2]
            if qb + 1 < 8:
                loadq(qb + 1)
            P.dma("sync", xt[:], fm(xa[:, qsl]), writes=["xt"], sem="xt")
            steps = [(h, kp) for h in range(16) for kp in range(17)]

            def S(i):
                h, kp = steps[i]
                g, oc, h2 = h // 4, h // 2, h % 2
                for e_ in range(2):
                    kt = 2 * kp + e_
                    P.mm(psS[i % 2][:, e_ * 512:(e_ + 1) * 512], Kz[h2][:, g, kt * 128:(kt + 1) * 128], QT[:, oc, :], True, True,
                         ["Kz", ("QT", 2 * oc, qb), ("QT", 2 * oc + 1, qb)], [f"psS{i % 2}"])

            def epi_a(h):
                o = h % 2
                P.cp("vector", osb[o][:], psO[o][:], [f"psO{o}"], [f"osb{o}"])

            def epi_b(h):
                oc, h2, o = h // 2, h % 2, h % 2
                hs = slice(h2 * 64, h2 * 64 + 64)
                P.mm(psB[:, :], sel[:, h2, :], osb[o][:], True, True, ["sel", f"osb{o}"], ["psB"])
                P.act(rb[o][hs, :], psB[hs, :], AF.Ln, ["psB"], [f"rb{o}"])
                P.act(rb[o][hs, :], rb[o][hs, :], AF.Exp, [f"rb{o}"], [f"rb{o}"], scale=-1.0)
                P.tt("gpsimd", QT[hs, oc, :], osb[o][hs, :], rb[o][hs, :], ALU.mult, [f"osb{o}", f"rb{o}"], [("QT", h, qb)])

            S(0)
            pend = {}
            for i, (h, kp) in enumerate(steps):
                g, h2, o = h // 4, h % 2, h % 2
                if i + 1 < len(steps):
                    S(i + 1)
                p_ = i % 3
                P.act(PT[p_][:], psS[i % 2][:, :], AF.Exp, [f"psS{i % 2}"], [f"PT{p_}"], scale=0.125)
                v0 = 65 + 65 * g if h2 == 0 else 1 + 65 * g
                for e_ in range(2):
                    kt = 2 * kp + e_
                    P.mm(psO[o][:, :], VA[:, kt, v0:v0 + 128], PT[p_][:, e_ * 512:(e_ + 1) * 512], kt == 0, kt == 33, [f"PT{p_}", "VA"], [f"psO{o}"])
                if kp == 16:
                    epi_a(h)
                    pend[i + 3] = h
                if i in pend:
                    epi_b(pend.pop(i))
            for k in sorted(pend):
                epi_b(pend[k])
            for oc in range(8):
                j = 0
                for c in range(8):
                    P.mm(pX[j][:, :], wo[:, c, oc * 128:(oc + 1) * 128], QT[:, c, :], c == 0, c == 7,
                         ["wo", ("QT", 2 * c, qb), ("QT", 2 * c + 1, qb)], [f"pX{j}"])
                P.stt(xt[:, oc, :], pX[j][:, :], gates[oc], xt[:, oc, :], ALU.mult, ALU.add, [f"pX{j}", "xt", "modv"], ["xt"])
            P.dma("sync", fm(xa[:, qsl]), xt[:], reads=["xt"], writes=[("xa", qb)], sem="xt")


IN_SHAPES = {
    "xin": [D, TT], "cvec": [128, 8, 2], "w_mod": [2, D, 6 * D], "b_mod": [2, 6 * D], "vecs": [128, NV, 8],
    "mlp_w1": [2, D, 4 * D], "mlp_w2": [2, 4 * D, D],
    "rwkv_wr": [D, D], "rwkv_wk": [D, D], "rwkv_wv": [D, D], "rwkv_wo": [D, D],
    "rwkv_w1": [2, D, 64], "rwkv_w2": [2, 64, D], "rwkv_a1": [2, D, 64], "rwkv_a2": [2, 64, D],
    "rwkv_g1": [D, 128], "rwkv_g2": [128, D], "lnw_st": [128, 8, 64], "lnb_st": [128, 8, 64],
    "attn_wq": [D, D], "attn_wkd": [D, 512], "attn_wv": [D, 256], "attn_wo": [D, D],
    "cosT": [128, T], "sinT": [128, T],
    "c_ident": [128, 128], "c_ones": [128, 128], "c_bones": [128, 128], "c_masks": [128, 4, 128],
    "c_perm": [128, 128], "c_rmask": [128, 256], "c_sel": [128, 2, 128],
}


class IO(dict):
    def __init__(self, nc):
        super().__init__()
        self.nc = nc
        self.used = []

    def __missing__(self, k):
        ap = self.nc.dram_tensor(k, IN_SHAPES[k], F32, kind="ExternalInput").ap()
        self[k] = ap
        self.used.append(k)
        return ap

    def scratch(self, name, shape, dtype):
        return self.nc.dram_tensor(name, list(shape), dtype, kind="Internal").ap()

    def output(self, name, shape, dtype=F32):
        return self.nc.dram_tensor(name, list(shape), dtype, kind="ExternalOutput").ap()


def build(stages="all", dbg=None):
    nc = bass.Bass("TRN2", target_bir_lowering=False)
    io = IO(nc)
    P = Prog(nc)
    G = {}
    outs = {}
    stage_init(P, io, G)
    xa = io.scratch("xa", [D, TT], F32)
    hb = io.scratch("hb", [D, TT], BF16)
    if stages == "t_mlp":
        outs["dbg_h"] = io.output("dbg_h", [D, TT], BF16)
        stage_norm(P, io, G, "n_t", io["xin"], ALL_TILES,
                   lambda ic: mod_scalars(G, 0, 1, ic)[0], lambda ic: mod_scalars(G, 0, 1, ic)[1],
                   lambda c0, tw, ic: fm(hb[:, c0:c0 + tw]), BF16)
        with P.phase("copy"):
            P.dma("sync", xa, io["xin"], writes=["xa"], sem="cpa")
            P.dma("sync", outs["dbg_h"], hb, writes=["o"], sem="cpb")
        stage_mlp(P, io, G, 0, ALL_TILES, xa, hb)
        outs["y"] = io.output("y", [D, TT])
        fin = [G["vec"][:, 4, c:c + 1] for c in range(8)]
        stage_norm(P, io, G, "final", xa, ALL_TILES, lambda ic: fin, lambda ic: None,
                   lambda c0, tw, ic: fm(outs["y"][:, c0:c0 + tw]), F32)
    if stages in ("all", "l0", "l1pre"):
        hp = io.scratch("hp", [D, 4608], F32)
        S = rw_scratch(io)
        with P.phase("zpad"):
            z = P.sb([128, 8, 64], F32)
            P.memset("vector", z[:], 0.0, ["z"])
            for k, o in enumerate((0, 64 + T, 4224, 4288 + C)):
                P.dma("sync", fm(hp[:, o:o + 64]), z[:], reads=["z"], writes=[("hpz", k)], sem=f"z{k}")

        def hdst(c0, tw, ic):
            o = 4288 if ic else 64 + c0
            return fm(hp[:, o:o + tw])

        def hbdst(c0, tw, ic):
            return fm(hb[:, c0:c0 + tw])

        def ms(l, kind, which):
            return lambda ic: mod_scalars(G, l, kind, ic)[which]

        stage_norm(P, io, G, "n_mix0", io["xin"], ALL_TILES, ms(0, 0, 0), ms(0, 0, 1), hdst, F32)
        stage_rwkv1(P, io, G, hp, S)
        stage_rwkv2(P, io, G, S, io["xin"], xa)
        stage_norm(P, io, G, "n_mlp0", xa, ALL_TILES, ms(0, 1, 0), ms(0, 1, 1), hbdst, BF16)
        stage_mlp(P, io, G, 0, ALL_TILES, xa, hb)
        if stages == "l0":
            outs["y"] = io.output("y", [D, TT])
            with P.phase("copyout"):
                P.dma("sync", outs["y"], xa, writes=["o"], sem="cpa")
        else:
            stage_norm(P, io, G, "n_mix1", xa, ALL_TILES, ms(1, 0, 0), ms(1, 0, 1), hbdst, BF16)
            with P.scope():
                QT = io.scratch("qtd", [D, T], BF16)
                Kz = [P.ssb([128, 4, TT], BF16, f"Kz{k}") for k in range(2)]
                VA = P.ssb([128, 34, 390], BF16, "VA")
                stage_qkv(P, io, G, hb, QT, Kz, VA)
                stage_attn(P, io, G, QT, Kz, VA, xa)
            if stages == "l1pre":
                outs["y"] = io.output("y", [D, TT])
                with P.phase("copyout"):
                    P.dma("sync", outs["y"], xa, writes=["o"], sem="cpa")
            else:
                stage_norm(P, io, G, "n_mlp1", xa, LAT_TILES, ms(1, 1, 0), ms(1, 1, 1), hbdst, BF16)
                stage_mlp(P, io, G, 1, LAT_TILES, xa, hb)
                outs["y"] = io.output("y", [D, T])
                fin = [G["vec"][:, 4, c:c + 1] for c in range(8)]
                stage_norm(P, io, G, "final", xa, LAT_TILES, lambda ic: fin, lambda ic: None,
                           lambda c0, tw, ic: fm(outs["y"][:, c0:c0 + tw]), F32)
    if stages == "t_rwkv":
        hp = io.scratch("hp", [D, 4608], F32)
        S = rw_scratch(io)
        with P.phase("zpad"):
            z = P.sb([128, 8, 64], F32)
            P.memset("vector", z[:], 0.0, ["z"])
            for k, o in enumerate((0, 64 + T, 4224, 4288 + C)):
                P.dma("sync", fm(hp[:, o:o + 64]), z[:], reads=["z"], writes=[("hpz", k)], sem=f"z{k}")
        def hdst(c0, tw, ic):
            o = 4288 if ic else 64 + c0
            return fm(hp[:, o:o + tw])
        stage_norm(P, io, G, "n_mix0", io["xin"], ALL_TILES,
                   lambda ic: mod_scalars(G, 0, 0, ic)[0], lambda ic: mod_scalars(G, 0, 0, ic)[1], hdst, F32)
        stage_rwkv1(P, io, G, hp, S)
        stage_rwkv2(P, io, G, S, io["xin"], xa)
        outs["y"] = io.output("y", [D, TT])
        with P.phase("copyout"):
            P.dma("sync", outs["y"], xa, writes=["o"], sem="cpa")
    P.close()
    return nc, io.used, list(outs.keys()), P


def fmv(v):
    return np.ascontiguousarray(np.asarray(v, np.float32).reshape(8, 128).T)


def host_consts():
    c = {}
    c["c_ident"] = np.eye(128, dtype=np.float32)
    c["c_ones"] = np.ones((128, 128), np.float32)
    blk = np.zeros((128, 128), np.float32)
    blk[:64, :64] = 1
    blk[64:, 64:] = 1
    c["c_bones"] = blk
    i = np.arange(64)
    us = (i[:, None] < i[None, :]).astype(np.float32)
    ui = (i[:, None] <= i[None, :]).astype(np.float32)
    m = np.zeros((128, 4, 128), np.float32)
    for k, mk in enumerate([us, ui, us.T, ui.T]):
        m[:64, k, :64] = mk
        m[64:, k, 64:] = mk
    c["c_masks"] = m
    Pm = np.zeros((128, 128), np.float32)
    for d in range(128):
        if d % 32 < 16:
            Pm[d, d + 16] = -1.0
        else:
            Pm[d, d - 16] = 1.0
    c["c_perm"] = np.ascontiguousarray(Pm.T)
    sel = np.zeros((128, 2, 128), np.float32)
    sel[64, 0, :] = 1.0
    sel[63, 1, :] = 1.0
    c["c_sel"] = sel
    rm = np.ones((128, 256), np.float32)
    rm[:, ::64] = 0
    c["c_rmask"] = rm
    t = np.arange(T)
    row = (t // 64).astype(np.float32)
    col = (t % 64).astype(np.float32)
    freqs = (np.float32(10000.0) ** (-np.arange(0, 32, 2, dtype=np.float32) / np.float32(32))).astype(np.float32)
    ang = np.zeros((64, T), np.float32)
    for d in range(64):
        pos = row if d < 32 else col
        ang[d] = pos * freqs[d % 16]
    c["cosT"] = np.ascontiguousarray(np.concatenate([np.cos(ang), np.cos(ang)], 0).astype(np.float32))
    c["sinT"] = np.ascontiguousarray(np.concatenate([np.sin(ang), np.sin(ang)], 0).astype(np.float32))
    return c


def host_inputs(inp, b):
    f = lambda k: np.asarray(inp[k], np.float32)
    d = {}
    d["xin"] = np.ascontiguousarray(np.concatenate([f("x")[b].T, f("ctx")[b].T], axis=1))
    d["cvec"] = np.ascontiguousarray(np.stack([fmv(f("c")[b]), fmv(f("c_ctx"))], axis=-1))
    return d


def host_shared(inp):
    f = lambda k: np.asarray(inp[k], np.float32)
    s = dict(host_consts())
    s["w_mod"] = f("w_mod")
    s["b_mod"] = f("b_mod")
    vl = [f("norm_mix")[0], f("norm_mix")[1], f("norm_mlp")[0], f("norm_mlp")[1], f("final_norm")]
    vl += [f("rwkv_mu")[0, j] for j in range(6)]
    vl += [f("rwkv_w0")[0, 0], f("rwkv_w0")[0, 1], f("rwkv_a0")[0, 0], f("rwkv_a0")[0, 1]]
    vl += [f("rwkv_k_k")[0], f("rwkv_k_a")[0], np.zeros(D, np.float32), f("rwkv_r_k")[0].reshape(-1)]
    vl += [np.tile(f("attn_q_norm")[0], 16), np.tile(f("attn_k_norm")[0], 16)]
    assert len(vl) == NV
    s["vecs"] = np.ascontiguousarray(np.stack([fmv(v) for v in vl], axis=1))
    s["mlp_w1"] = f("mlp_w1")
    s["mlp_w2"] = f("mlp_w2")
    for k in ("wr", "wk", "wv", "wo", "w1", "w2", "a1", "a2", "g1", "g2"):
        s["rwkv_" + k] = f("rwkv_" + k)[0]
    lw = f("rwkv_ln_w")[0].reshape(8, 2, 64)
    lb = f("rwkv_ln_b")[0].reshape(8, 2, 64)
    s["lnw_st"] = np.ascontiguousarray(np.repeat(lw.transpose(1, 0, 2), 64, axis=0))
    s["lnb_st"] = np.ascontiguousarray(np.repeat(lb.transpose(1, 0, 2), 64, axis=0))
    wqkv = f("attn_wqkv")[0]
    s["attn_wq"] = np.ascontiguousarray(wqkv[:, :1024])
    wk = wqkv[:, 1024:1280].reshape(D, 4, 64)
    s["attn_wkd"] = np.ascontiguousarray(np.concatenate([wk, wk], axis=2).reshape(D, 512))
    s["attn_wv"] = np.ascontiguousarray(wqkv[:, 1280:1536])
    s["attn_wo"] = f("attn_wo")[0]
    return s


_CACHE = {}


def kernel(**inputs):
    if "prog" not in _CACHE:
        _CACHE["prog"] = build("all")
    nc, used, outnames, _ = _CACHE["prog"]
    shared = host_shared(inputs)
    in_maps = []
    for b in range(NCORES):
        hi = host_inputs(inputs, b)
        hi.update(shared)
        in_maps.append({k: hi[k] for k in used})
    res = run_bass_kernel_spmd(nc, in_maps, core_ids=list(range(NCORES)))
    out = np.stack([np.ascontiguousarray(res.results[b]["y"].T) for b in range(NCORES)], axis=0)
    return out.astype(np.float32)
```

```python
from contextlib import ExitStack, contextmanager
import re as re_mod
import numpy as np
import concourse.bass as bass
import concourse.mybir as mybir
from concourse.bass_utils import run_bass_kernel_spmd

F32 = mybir.dt.float32
BF16 = mybir.dt.bfloat16
AF = mybir.ActivationFunctionType
ALU = mybir.AluOpType
AX = mybir.AxisListType

D = 1024
T = 4096
C = 256
TT = T + C
NCORES = 8
C0 = float(np.exp(-0.5))
NV = 21
ENGS = ("tensor", "vector", "scalar", "gpsimd", "sync")


class Prog:
    def __init__(self, nc):
        self.nc = nc
        self.ges = ExitStack()
        self.sems = {}
        self.cnt = {}
        self.dpool = {False: [], True: []}
        self.seen = {e: {} for e in ENGS}
        self.n = 0
        self.pes = None
        self.total_ops = 0

    def _alloc(self, es, fn, shape, dtype, name):
        self.n += 1
        return es.enter_context(fn(name or f"t{self.n}", list(shape), dtype))

    def gsb(self, shape, dtype, name=None):
        return self._alloc(self.ges, self.nc.sbuf_tensor, shape, dtype, name)

    def sb(self, shape, dtype, name=None):
        return self._alloc(self.pes, self.nc.sbuf_tensor, shape, dtype, name)

    @contextmanager
    def scope(self):
        self.ses = ExitStack()
        yield self
        self.ses.close()
        self.ses = None

    def ssb(self, shape, dtype, name=None):
        return self._alloc(self.ses, self.nc.sbuf_tensor, shape, dtype, name)

    def ps(self, shape, dtype, name=None):
        return self._alloc(self.pes, self.nc.psum_tensor, shape, dtype, name)

    @contextmanager
    def phase(self, name):
        self.ops = []
        self.last_w = {}
        self.readers = {}
        self.last_dma = {}
        self.pes = ExitStack()
        self.pname = name
        yield self
        self._emit()
        self.pes.close()
        self.pes = None

    _PSUM_RE = re_mod.compile(r"^(pp|pa|pb|pq|pf|ps\w*|pX|py|pS|ptr|pw)\d*$")

    def _deps(self, reads, writes):
        extra = tuple(r for r in reads if isinstance(r, str) and self._PSUM_RE.match(r) and r not in writes)
        if extra:
            writes = tuple(writes) + extra
        deps = {}
        for r in reads:
            if r in self.last_w:
                deps.setdefault(self.last_w[r], set()).add("RAW")
        for w in writes:
            if w in self.last_w:
                deps.setdefault(self.last_w[w], set()).add("WAW")
            for rd in self.readers.get(w, ()):
                deps.setdefault(rd, set()).add("WAR")
        idx = len(self.ops)
        for r in reads:
            self.readers.setdefault(r, []).append(idx)
        for w in writes:
            self.last_w[w] = idx
            self.readers[w] = []
        return deps

    def op(self, eng, fn, reads=(), writes=()):
        deps = self._deps(tuple(reads), tuple(writes))
        self.ops.append(dict(eng=eng, fn=fn, deps=deps, dma=None))
        return len(self.ops) - 1

    def dma(self, queue, out, in_, reads=(), writes=(), sem=None):
        deps = self._deps(tuple(reads), tuple(writes))
        prev = self.last_dma.get(sem)
        if prev is not None:
            deps.setdefault(prev, set()).add("SER")
        idx = len(self.ops)
        self.last_dma[sem] = idx
        self.ops.append(dict(eng=queue, fn=lambda e: e.dma_start(out=out, in_=in_), deps=deps, dma=sem))
        return idx

    def _emit(self):
        nc = self.nc
        ops = self.ops
        if self.last_dma:
            ops.append(dict(eng="sync", fn=None, deps={i: {"FIN"} for i in self.last_dma.values()}, dma=None))
        self.total_ops += len(ops)

        def needs_wait(x, d, kinds):
            if d["dma"] is not None or x["dma"] is not None:
                return True
            if d["eng"] != x["eng"]:
                return True
            if x["eng"] == "tensor":
                return False
            return bool(kinds & {"RAW", "FIN"})

        signal = [False] * len(ops)
        for x in ops:
            for di, kinds in x["deps"].items():
                d = ops[di]
                if d["dma"] is None and needs_wait(x, d, kinds):
                    signal[di] = True
        dkeys = {}
        nk = {False: 0, True: 0}
        for o in ops:
            if o["dma"] is not None and o["dma"] not in dkeys:
                sw = o["eng"] == "gpsimd"
                dkeys[o["dma"]] = (sw, nk[sw])
                nk[sw] += 1
        for sw in (False, True):
            while len(self.dpool[sw]) < nk[sw]:
                h = self.ges.enter_context(nc.semaphore(f"dq{int(sw)}_{len(self.dpool[sw])}"))
                self.dpool[sw].append([h, 0])
        for e in ENGS:
            if e not in self.sems:
                self.sems[e] = self.ges.enter_context(nc.semaphore(f"e_{e}"))
        token = [None] * len(ops)
        for i, o in enumerate(ops):
            if o["dma"] is not None:
                dk = dkeys[o["dma"]]
                slot = self.dpool[dk[0]][dk[1]]
                slot[1] += 16
                token[i] = (("d", dk), slot[1])
            elif signal[i]:
                self.cnt[o["eng"]] = self.cnt.get(o["eng"], 0) + 1
                token[i] = (("e", o["eng"]), self.cnt[o["eng"]])
        per_eng = {e: [] for e in ENGS}
        for i, o in enumerate(ops):
            per_eng[o["eng"]].append(i)

        def semh(key):
            return self.dpool[key[1][0]][key[1][1]][0] if key[0] == "d" else self.sems[key[1]]

        def run(engname, eng):
            seen = self.seen[engname]
            for i in per_eng[engname]:
                o = ops[i]
                waits = {}
                for di, kinds in o["deps"].items():
                    d = ops[di]
                    if not needs_wait(o, d, kinds):
                        continue
                    key, val = token[di]
                    if waits.get(key, 0) < val:
                        waits[key] = val
                for key, val in waits.items():
                    if seen.get(key, 0) >= val:
                        continue
                    seen[key] = val
                    eng.wait_ge(semh(key), val)
                if o["fn"] is None:
                    continue
                ins = o["fn"](eng)
                if o["dma"] is not None:
                    ins.then_inc(semh(token[i][0]), 16)
                elif signal[i]:
                    ins.then_inc(self.sems[engname], 1)

        with nc.Block() as block:
            @block.sync
            def _(e):
                run("sync", e)

            @block.tensor
            def _(e):
                run("tensor", e)

            @block.vector
            def _(e):
                run("vector", e)

            @block.scalar
            def _(e):
                run("scalar", e)

            @block.gpsimd
            def _(e):
                run("gpsimd", e)

    def close(self):
        self.ges.close()

    def mm(self, out, lhsT, rhs, start, stop, r, w):
        self.op("tensor", lambda e: e.matmul(out, lhsT=lhsT, rhs=rhs, start=start, stop=stop), r, w)

    def tr(self, out, in_, ident, r, w):
        self.op("tensor", lambda e: e.transpose(out, in_, ident), r, w)

    def tt(self, eng, out, in0, in1, op, r, w):
        self.op(eng, lambda e: e.tensor_tensor(out=out, in0=in0, in1=in1, op=op), r, w)

    def ts(self, eng, out, in0, s1, s2, op0, op1, r, w):
        if op1 is None:
            self.op(eng, lambda e: e.tensor_scalar(out=out, in0=in0, scalar1=s1, scalar2=None, op0=op0), r, w)
        else:
            self.op(eng, lambda e: e.tensor_scalar(out=out, in0=in0, scalar1=s1, scalar2=s2, op0=op0, op1=op1), r, w)

    def stt(self, out, in0, scalar, in1, op0, op1, r, w):
        self.op("vector", lambda e: e.scalar_tensor_tensor(out=out, in0=in0, scalar=scalar, in1=in1, op0=op0, op1=op1), r, w)

    def act(self, out, in_, func, r, w, bias=None, scale=None):
        kw = {}
        if bias is not None:
            kw["bias"] = bias
        if scale is not None:
            kw["scale"] = scale
        self.op("scalar", lambda e: e.activation(out=out, in_=in_, func=func, **kw), r, w)

    def cp(self, eng, out, in_, r, w):
        if eng == "scalar":
            self.op(eng, lambda e: e.activation(out=out, in_=in_, func=AF.Copy), r, w)
        else:
            self.op(eng, lambda e: e.tensor_copy(out=out, in_=in_), r, w)

    def memset(self, eng, ap, val, w):
        self.op(eng, lambda e: e.memset(ap, val), (), w)


def fm(ap2d):
    return ap2d.rearrange("(c p) n -> p c n", p=128)


LAT_TILES = [(i * 512, 512, False) for i in range(8)]
ALL_TILES = LAT_TILES + [(T, 256, True)]


def stage_init(P, io, G):
    nc = P.nc
    G["identf"] = P.gsb([128, 128], F32, "identf")
    G["identb"] = P.gsb([128, 128], BF16, "identb")
    G["onesb"] = P.gsb([128, 128], BF16, "onesb")
    G["bones"] = P.gsb([128, 128], BF16, "bones")
    G["masks"] = P.gsb([128, 4, 128], BF16, "masks")
    G["perm"] = P.gsb([128, 128], BF16, "perm")
    G["rmask"] = P.gsb([128, 256], F32, "rmask")
    G["vec"] = P.gsb([128, NV, 8], F32, "vec")
    G["modv"] = P.gsb([128, 2, 6, 8, 2], F32, "modv")
    G["gg"] = P.gsb([128, 2, 2, 8, 2], F32, "gg")
    with P.phase("init"):
        P.dma("sync", G["identf"][:], io["c_ident"], writes=["identf"], sem="identf")
        P.dma("sync", G["rmask"][:], io["c_rmask"], writes=["rmask"], sem="rmask")
        P.dma("sync", G["vec"][:], io["vecs"], writes=["vec"], sem="vec")
        P.dma("gpsimd", G["identb"][:], io["c_ident"], writes=["identb"], sem="identb")
        P.dma("gpsimd", G["onesb"][:], io["c_ones"], writes=["onesb"], sem="onesb")
        P.dma("gpsimd", G["bones"][:], io["c_bones"], writes=["bones"], sem="bones")
        P.dma("gpsimd", G["masks"][:], io["c_masks"], writes=["masks"], sem="masks")
        P.dma("gpsimd", G["perm"][:], io["c_perm"], writes=["perm"], sem="perm")
        vec = G["vec"]
        P.ts("vector", vec[:, 17, :], vec[:, 16, :], -1.0, 1.0, ALU.mult, ALU.add, ["vec"], ["vec"])
        sv = P.sb([128, 8, 2], F32)
        svs = P.sb([128, 8, 2], F32)
        P.dma("sync", sv[:], io["cvec"], writes=["sv"], sem="sv")
        P.act(svs[:], sv[:], AF.Silu, ["sv"], ["svs"])
        brow = P.sb([2, 2 * 6144], F32)
        row = P.sb([2, 2 * 6144], F32)
        P.dma("sync", brow[:], io["b_mod"].rearrange("l n -> (l n)").partition_broadcast(2), writes=["brow"], sem="brow")
        wt = [P.sb([128, 8, 512], F32) for _ in range(2)]
        psr = [P.ps([128, 512], F32) for _ in range(2)]
        pst = P.ps([128, 512], F32)
        k = 0
        for l in range(2):
            for nb in range(12):
                b = k % 2
                k += 1
                P.dma("sync", wt[b][:], fm(io["w_mod"][l, :, nb * 512:(nb + 1) * 512]), writes=[f"wt{b}"], sem=f"wt{b}")
                for c in range(8):
                    P.mm(psr[b][0:2, :], svs[:, c, :], wt[b][:, c, :], c == 0, c == 7, ["svs", f"wt{b}"], [f"psr{b}"])
                o = l * 6144 + nb * 512
                P.tt("vector", row[:, o:o + 512], psr[b][0:2, :], brow[:, o:o + 512], ALU.add, [f"psr{b}", "brow"], ["row"])
        for l in range(2):
            for blk in range(48):
                o = l * 6144 + blk * 128
                P.tr(pst[:, l * 96 + blk * 2:l * 96 + blk * 2 + 2], row[0:2, o:o + 128], G["identf"][0:2, 0:2], ["row", "identf"], ["pst"])
        P.cp("vector", G["modv"][:].rearrange("p l m c j -> p (l m c j)"), pst[:, 0:192], ["pst"], ["modv"])
        modv, gg = G["modv"], G["gg"]
        for l in range(2):
            for kind in range(2):
                sc = modv[:, l, 1 + 3 * kind, :, :]
                nv = vec[:, (0 if kind == 0 else 2) + l, :].unsqueeze(2).broadcast_to([128, 8, 2])
                P.ts("vector", gg[:, l, kind, :, :], sc, 1.0, None, ALU.add, None, ["modv"], ["gg"])
                P.tt("vector", gg[:, l, kind, :, :], gg[:, l, kind, :, :], nv, ALU.mult, ["gg", "vec"], ["gg"])


def mod_scalars(G, l, kind, isctx):
    j = 1 if isctx else 0
    gains = [G["gg"][:, l, kind, c, j:j + 1] for c in range(8)]
    shifts = [G["modv"][:, l, 3 * kind, c, j:j + 1] for c in range(8)]
    gates = [G["modv"][:, l, 3 * kind + 2, c, j:j + 1] for c in range(8)]
    return gains, shifts, gates


def stage_norm(P, io, G, name, src, tiles, gains_fn, shifts_fn, dst_fn, out_dtype):
    with P.phase(name):
        xt = [P.sb([128, 8, 512], F32) for _ in range(2)]
        sq = P.sb([128, 8, 512], BF16)
        lnv = P.sb([128, 512], F32)
        rstd = P.sb([128, 512], F32)
        tmp = [P.sb([128, 512], F32) for _ in range(2)]
        ho = [P.sb([128, 8, 512], out_dtype) for _ in range(2)]
        ps = [P.ps([128, 512], F32) for _ in range(2)]

        def load(i):
            c0, tw, _ = tiles[i]
            b = i % 2
            P.dma("sync", xt[b][:, :, :tw], fm(src[:, c0:c0 + tw]), writes=[f"xt{b}"], sem=f"xt{b}")

        load(0)
        for i, (c0, tw, isctx) in enumerate(tiles):
            b = i % 2
            if i + 1 < len(tiles):
                load(i + 1)
            gains = gains_fn(isctx)
            shifts = shifts_fn(isctx)
            P.act(sq[:, :, :tw], xt[b][:, :, :tw], AF.Square, [f"xt{b}"], ["sq"])
            for c in range(8):
                P.mm(ps[b][:, :tw], G["onesb"][:], sq[:, c, :tw], c == 0, c == 7, ["sq", "onesb"], [f"ps{b}"])
            P.act(lnv[:, :tw], ps[b][:, :tw], AF.Ln, [f"ps{b}"], ["lnv"], bias=1e-6, scale=1.0 / D)
            P.act(rstd[:, :tw], lnv[:, :tw], AF.Exp, ["lnv"], ["rstd"], scale=-0.5)
            for c in range(8):
                if shifts is None:
                    P.stt(ho[b][:, c, :tw], xt[b][:, c, :tw], gains[c], rstd[:, :tw], ALU.mult, ALU.mult,
                          [f"xt{b}", "rstd", "vec", "gg"], [f"ho{b}"])
                else:
                    t = tmp[c % 2]
                    P.stt(t[:, :tw], xt[b][:, c, :tw], gains[c], rstd[:, :tw], ALU.mult, ALU.mult,
                          [f"xt{b}", "rstd", "vec", "gg"], [f"tmp{c % 2}"])
                    P.act(ho[b][:, c, :tw], t[:, :tw], AF.Identity, [f"tmp{c % 2}", "modv"], [f"ho{b}"], bias=shifts[c])
            P.dma("sync", dst_fn(c0, tw, isctx), ho[b][:, :, :tw], reads=[f"ho{b}"], writes=[("dst", i)], sem=f"ho{b}")


def stage_mlp(P, io, G, l, tiles, xa, hb):
    for half in range(2):
        with P.phase(f"mlp{l}{half}"):
            w1 = P.sb([128, 8, 2048], BF16)
            w2 = P.sb([128, 16, 1024], BF16)
            for q in range(2):
                P.dma("gpsimd", w1[:, :, q * 1024:(q + 1) * 1024],
                      fm(io["mlp_w1"][l, :, half * 2048 + q * 1024: half * 2048 + (q + 1) * 1024]), writes=["w1"], sem=f"w1{q}")
                P.dma("gpsimd", w2[:, q * 8:(q + 1) * 8, :],
                      io["mlp_w2"][l, half * 2048 + q * 1024: half * 2048 + (q + 1) * 1024, :].rearrange("(f p) n -> p f n", p=128),
                      writes=["w2"], sem=f"w2{q}")
            xt = [P.sb([128, 8, 512], F32) for _ in range(2)]
            ht = [P.sb([128, 8, 512], BF16) for _ in range(2)]
            h1 = P.sb([128, 16, 512], BF16)
            r1 = [P.sb([128, 512], F32) for _ in range(2)]
            ps = [P.ps([128, 512], F32) for _ in range(4)]

            def load(i):
                c0, tw, _ = tiles[i]
                b = i % 2
                P.dma("sync", ht[b][:, :, :tw], fm(hb[:, c0:c0 + tw]), writes=[f"ht{b}"], sem=f"ht{b}")
                P.dma("sync", xt[b][:, :, :tw], fm(xa[:, c0:c0 + tw]), reads=[("xa", i)], writes=[f"xt{b}"], sem=f"xt{b}")

            load(0)
            for i, (c0, tw, isctx) in enumerate(tiles):
                b = i % 2
                if i + 1 < len(tiles):
                    load(i + 1)
                _, _, gates = mod_scalars(G, l, 1, isctx)
                for fc in range(16):
                    pb = fc % 2
                    for c in range(8):
                        P.mm(ps[pb][:, :tw], w1[:, c, fc * 128:(fc + 1) * 128], ht[b][:, c, :tw], c == 0, c == 7,
                             ["w1", f"ht{b}"], [f"ps{pb}"])
                    P.act(r1[pb][:, :tw], ps[pb][:, :tw], AF.Relu, [f"ps{pb}"], [f"r1{pb}"])
                    P.tt("gpsimd", h1[:, fc, :tw], r1[pb][:, :tw], r1[pb][:, :tw], ALU.mult, [f"r1{pb}"], [("h1", fc)])
                for oc in range(8):
                    pb = 2 + oc % 2
                    for fc in range(16):
                        P.mm(ps[pb][:, :tw], w2[:, fc, oc * 128:(oc + 1) * 128], h1[:, fc, :tw], fc == 0, fc == 15,
                             ["w2", ("h1", fc)], [f"ps{pb}"])
                    P.stt(xt[b][:, oc, :tw], ps[pb][:, :tw], gates[oc], xt[b][:, oc, :tw], ALU.mult, ALU.add,
                          [f"ps{pb}", f"xt{b}", "modv"], [f"xt{b}"])
                P.dma("sync", fm(xa[:, c0:c0 + tw]), xt[b][:, :, :tw], reads=[f"xt{b}"], writes=[("xa", i)], sem=f"xt{b}")


RW_ORDER1 = [(True, 0)] + [(False, i) for i in range(16)]
RW_ORDER2 = [(True, 0)] + [(False, i) for i in range(15, -1, -1)]


def rw_scratch(io):
    S = {}
    S["yp"] = io.scratch("rw_yp", [17, 8, 128, 256], F32)
    S["sadd"] = io.scratch("rw_sadd", [17, 8, 128, 256], F32)
    S["vst"] = io.scratch("rw_vst", [17, 8, 128, 256], F32)
    S["gst"] = io.scratch("rw_gst", [17, 8, 128, 256], F32)
    S["gyb"] = io.scratch("rw_gyb", [17, 8, 128, 512], BF16)
    S["gsb"] = io.scratch("rw_gsb", [17, 8, 128, 512], BF16)
    S["gamb"] = io.scratch("rw_gamb", [17, 128, 32], F32)
    S["bon"] = io.scratch("rw_bon", [17, 128, 32], F32)
    return S


def stage_rwkv1(P, io, G, hp, S, dbg=None):
    vec, masks, identb, identf, bones, onesb, rmask = (G[k] for k in ("vec", "masks", "identb", "identf", "bones", "onesb", "rmask"))
    with P.phase("rwkv1"):
        wr = P.sb([128, 8, 1024], BF16)
        wk = P.sb([128, 8, 1024], BF16)
        wv = P.sb([128, 8, 1024], BF16)
        for w, nm in ((wr, "rwkv_wr"), (wk, "rwkv_wk"), (wv, "rwkv_wv")):
            P.dma("gpsimd", w[:], fm(io[nm]), writes=[nm], sem=nm)
        lw1 = P.sb([128, 8, 128], BF16)
        la1 = P.sb([128, 8, 128], BF16)
        g1 = P.sb([128, 8, 128], BF16)
        for d in range(2):
            P.dma("gpsimd", lw1[:, :, d * 64:(d + 1) * 64], io["rwkv_w1"][d].rearrange("(c p) j -> p c j", p=128), writes=["lw1"], sem=f"lw1{d}")
            P.dma("gpsimd", la1[:, :, d * 64:(d + 1) * 64], io["rwkv_a1"][d].rearrange("(c p) j -> p c j", p=128), writes=["la1"], sem=f"la1{d}")
        P.dma("gpsimd", g1[:], io["rwkv_g1"].rearrange("(c p) j -> p c j", p=128), writes=["g1"], sem="g1")
        w2s = P.sb([128, 1024], BF16)
        a2s = P.sb([128, 1024], BF16)
        g2 = P.sb([128, 1024], BF16)
        P.dma("gpsimd", w2s[:], io["rwkv_w2"].rearrange("d j f -> (d j) f"), writes=["w2s"], sem="w2s")
        P.dma("gpsimd", a2s[:], io["rwkv_a2"].rearrange("d j f -> (d j) f"), writes=["a2s"], sem="a2s")
        P.dma("gpsimd", g2[:], io["rwkv_g2"], writes=["g2"], sem="g2")

        hh = P.sb([128, 8, 384], F32)
        xx = P.sb([128, 8, 256], F32)
        xr = P.sb([128, 8, 256], BF16)
        xk = P.sb([128, 8, 256], BF16)
        xv = P.sb([128, 8, 256], BF16)
        xrot = P.sb([128, 8, 256], BF16)
        lwt = P.sb([128, 256], BF16)
        lat = P.sb([128, 256], BF16)
        sg = P.sb([128, 256], BF16)
        f32t = {}
        for nm in ("r", "k", "sw0", "sw1", "ag0", "ag1", "kq", "lnv", "rs", "kkn", "fac", "kd0", "kd1", "b0", "b1",
                   "L", "Lx", "Lb", "E1", "E2", "E3", "ks"):
            f32t[nm] = P.sb([128, 256], F32, "t_" + nm)
        sqb = P.sb([128, 256], BF16)
        RK = P.sb([128, 4, 2, 64], BF16)
        VTbd = P.sb([128, 4, 128], F32)
        GTbd = P.sb([128, 4, 128], F32)
        Vf = P.sb([128, 4, 64], F32)
        Gf = P.sb([128, 4, 64], F32)
        YPs = P.sb([128, 4, 64], F32)
        SAs = P.sb([128, 4, 64], F32)
        gamb_t = P.sb([128, 8, 4], F32)
        bon_t = P.sb([128, 8, 4], F32)
        Sf = P.sb([128, 8, 64], BF16)
        ARq = [[P.sb([128, 4, 2, 128], BF16, f"AR{q}{d}") for d in range(2)] for q in range(2)]
        KTq = [[P.sb([128, 4, 128], BF16, f"KT{q}{d}") for d in range(2)] for q in range(2)]
        BTq = [[P.sb([128, 4, 128], BF16, f"BT{q}{d}") for d in range(2)] for q in range(2)]
        Vbq = [P.sb([128, 4, 64], BF16, f"Vb{q}") for q in range(3)]
        gamq = [[P.sb([128, 4], F32, f"gam{q}{d}") for d in range(2)] for q in range(3)]
        inv = []
        for d in range(2):
            st = {}
            for nm, shp in (("Atok", [128, 4, 128]), ("Btok", [128, 4, 128]), ("MQ", [128, 4, 256]), ("MWa", [128, 4, 2, 128]),
                            ("MWb", [128, 4, 2, 128]), ("MTa", [128, 4, 128]), ("MTb", [128, 4, 128])):
                st[nm] = P.sb(shp, BF16, f"i{d}_{nm}")
            inv.append(st)
        fin = []
        for q in range(2):
            row = []
            for d in range(2):
                st = {}
                for nm, shp in (("Ktok", [128, 4, 128]), ("NP", [128, 4, 256]), ("XW", [128, 4, 256]), ("NVb", [128, 4, 64]),
                                ("GY", [128, 4, 128]), ("GS", [128, 4, 128])):
                    st[nm] = P.sb(shp, BF16, f"f{q}{d}_{nm}")
                row.append(st)
            fin.append(row)
        ppt = [P.ps([128, 512], F32) for _ in range(2)]
        pp = [t_[:, 0:256] for t_ in ppt]
        pf = P.ps([128, 512], F32)
        pb = [P.ps([128, 512], F32) for _ in range(5)]
        cnt = {"pp": 0, "pb": 0}
        nmod = {"pp": 2, "pb": 5}

        def nxt(kind):
            i = cnt[kind] % nmod[kind]
            cnt[kind] += 1
            return i

        for q in range(2):
            for d in range(2):
                P.memset("gpsimd", ARq[q][d][:], 0.0, [f"AR{q}{d}"])
                P.memset("gpsimd", KTq[q][d][:], 0.0, [f"KT{q}{d}"])
                P.memset("gpsimd", BTq[q][d][:], 0.0, [f"BT{q}{d}"])
        P.memset("gpsimd", RK[:], 0.0, ["RK"])
        P.memset("gpsimd", VTbd[:], 0.0, ["VTbd"])
        P.memset("gpsimd", GTbd[:], 0.0, ["GTbd"])
        P.memset("gpsimd", Sf[:], 0.0, [("Sf", p) for p in range(8)])

        def v3(ap):
            return ap.rearrange("p (u s) -> p u s", s=64)

        def u128(ap):
            return ap.rearrange("p (u x) -> p u x", x=128)

        def load_hh(ti):
            isctx, idx = RW_ORDER1[ti]
            off = 4288 if isctx else 64 + 256 * idx
            P.dma("sync", hh[:], fm(hp[:, off - 64: off + 320]), writes=["hh"], sem="hh")

        def proj8(w_cols_fn, xb, bn, extra_r):
            i = nxt("pp")
            for c in range(8):
                P.mm(pp[i], w_cols_fn(c), xb[:, c, :], c == 0, c == 7, [(bn, c)] + extra_r, [f"pp{i}"])
            return i

        def tprep(ti):
            isctx, idx = RW_ORDER1[ti]
            hc = hh[:, :, 64:320]
            XXW = [("xx", c) for c in range(8)]
            if not isctx:
                h4 = hh[:, :, 64:320].rearrange("p c (r w) -> p c r w", w=64)
                x4 = xx[:].rearrange("p c (r w) -> p c r w", w=64)
                P.tt("vector", x4[:, 0:2, :, 1:64], h4[:, 0:2, :, 0:63], h4[:, 0:2, :, 1:64], ALU.subtract, ["hh"], XXW[0:2])
                P.ts("gpsimd", x4[:, 0:2, :, 0:1], h4[:, 0:2, :, 0:1], -1.0, 0.0, ALU.mult, ALU.add, ["hh"], [("xxe", 0)])
                P.tt("vector", x4[:, 2:4, :, 0:63], h4[:, 2:4, :, 1:64], h4[:, 2:4, :, 0:63], ALU.subtract, ["hh"], XXW[2:4])
                P.ts("gpsimd", x4[:, 2:4, :, 63:64], h4[:, 2:4, :, 63:64], -1.0, 0.0, ALU.mult, ALU.add, ["hh"], [("xxe", 1)])
                P.tt("gpsimd", xx[:, 4:6, :], hh[:, 4:6, 0:256], hh[:, 4:6, 64:320], ALU.subtract, ["hh"], XXW[4:6])
                P.tt("gpsimd", xx[:, 6:8, :], hh[:, 6:8, 128:384], hh[:, 6:8, 64:320], ALU.subtract, ["hh"], XXW[6:8])
            else:
                P.tt("vector", xx[:, 0:4, :], hh[:, 0:4, 63:319], hh[:, 0:4, 64:320], ALU.subtract, ["hh"], XXW[0:4] + [("xxe", 0)])
                P.tt("gpsimd", xx[:, 4:8, :], hh[:, 4:8, 65:321], hh[:, 4:8, 64:320], ALU.subtract, ["hh"], XXW[4:8] + [("xxe", 1)])
            yield

            def mk_xj(j, buf, bn):
                for c in range(8):
                    P.stt(buf[:, c, :], xx[:, c, :], vec[:, 5 + j, c:c + 1], hc[:, c, :], ALU.mult, ALU.add,
                          [("xx", c), ("xxe", 0), ("xxe", 1), "hh", "vec"], [(bn, c)])

            mk_xj(1, xrot, "xrot")
            yield
            i = proj8(lambda c: lw1[:, c, :], xrot, "xrot", ["lw1"])
            P.act(lwt[:], pp[i], AF.Tanh, [f"pp{i}"], ["lwt"])
            yield
            mk_xj(4, xrot, "xrot")
            yield
            i = proj8(lambda c: la1[:, c, :], xrot, "xrot", ["la1"])
            P.cp("scalar", lat[:], pp[i], [f"pp{i}"], ["lat"])
            yield
            mk_xj(5, xrot, "xrot")
            yield
            i = proj8(lambda c: g1[:, c, :], xrot, "xrot", ["g1"])
            P.act(sg[:], pp[i], AF.Sigmoid, [f"pp{i}"], ["sg"])
            yield
            mk_xj(0, xr, "xr")
            yield
            mk_xj(2, xk, "xk")
            yield
            mk_xj(3, xv, "xv")
            if ti + 1 < len(RW_ORDER1):
                load_hh(ti + 1)
            yield

        def prep(ti, oc, q, z):
            isctx, idx = RW_ORDER1[ti]
            tg = 16 if isctx else idx
            cs = slice(oc * 128, (oc + 1) * 128)
            t = f32t
            AR, KT, BT, Vb, gam = ARq[q], KTq[q], BTq[q], Vbq[z], gamq[z]
            i = proj8(lambda c: wr[:, c, cs], xr, "xr", ["rwkv_wr"])
            P.cp("scalar", t["r"][:], pp[i], [f"pp{i}"], ["r"])
            i = proj8(lambda c: wk[:, c, cs], xk, "xk", ["rwkv_wk"])
            P.cp("scalar", t["k"][:], pp[i], [f"pp{i}"], ["k"])
            i = proj8(lambda c: wv[:, c, cs], xv, "xv", ["rwkv_wv"])
            vt4 = VTbd[:].rearrange("p u (h s) -> p u h s", h=2)
            for h2 in range(2):
                sl = slice(h2 * 64, (h2 + 1) * 64)
                P.cp("scalar", vt4[sl, :, h2, :], v3(pp[i][sl, :]), [f"pp{i}"], ["VTbd"])
            i = nxt("pp")
            P.mm(pp[i], g2[:, cs], sg[:], True, True, ["g2", "sg"], [f"pp{i}"])
            gt4 = GTbd[:].rearrange("p u (h s) -> p u h s", h=2)
            for h2 in range(2):
                sl = slice(h2 * 64, (h2 + 1) * 64)
                P.cp("scalar", gt4[sl, :, h2, :], v3(pp[i][sl, :]), [f"pp{i}"], ["GTbd"])
            yield
            j = nxt("pb")
            for u in range(4):
                P.tr(pb[j][:, u * 128:(u + 1) * 128], VTbd[:, u, :], identf[:], ["VTbd", "identf"], [f"pb{j}"])
            pv = u128(pb[j][:])
            for h2 in range(2):
                sl = slice(h2 * 64, (h2 + 1) * 64)
                P.cp("scalar", Vf[sl, :, :], pv[sl, :, h2 * 64:(h2 + 1) * 64], [f"pb{j}"], ["Vf"])
            P.cp("gpsimd", Vb[:], Vf[:], ["Vf"], [f"Vb{z}"])
            P.dma("sync", S["vst"][tg, oc].rearrange("p (u s) -> p u s", s=64), Vf[:], reads=["Vf"], writes=[("vst", tg, oc)], sem="Vf")
            j = nxt("pb")
            for u in range(4):
                P.tr(pb[j][:, u * 128:(u + 1) * 128], GTbd[:, u, :], identf[:], ["GTbd", "identf"], [f"pb{j}"])
            pv = u128(pb[j][:])
            for h2 in range(2):
                sl = slice(h2 * 64, (h2 + 1) * 64)
                P.cp("scalar", Gf[sl, :, :], pv[sl, :, h2 * 64:(h2 + 1) * 64], [f"pb{j}"], ["Gf"])
            P.dma("sync", S["gst"][tg, oc].rearrange("p (u s) -> p u s", s=64), Gf[:], reads=["Gf"], writes=[("gst", tg, oc)], sem="Gf")
            yield
            for d in range(2):
                dl = slice(d * 64, (d + 1) * 64)
                i = nxt("pp")
                P.mm(pp[i], w2s[dl, cs], lwt[dl, :], True, True, ["w2s", "lwt"], [f"pp{i}"])
                P.act(t[f"sw{d}"][:], pp[i], AF.Sigmoid, [f"pp{i}", "vec"], [f"sw{d}"], bias=vec[:, 11 + d, oc:oc + 1])
                i = nxt("pp")
                P.mm(pp[i], a2s[dl, cs], lat[dl, :], True, True, ["a2s", "lat"], [f"pp{i}"])
                P.act(t[f"ag{d}"][:], pp[i], AF.Sigmoid, [f"pp{i}", "vec"], [f"ag{d}"], bias=vec[:, 13 + d, oc:oc + 1])
            yield
            P.ts("vector", t["kq"][:], t["k"][:], vec[:, 15, oc:oc + 1], None, ALU.mult, None, ["k", "vec"], ["kq"])
            P.act(sqb[:], t["kq"][:], AF.Square, ["kq"], ["sqb"])
            i = nxt("pp")
            P.mm(pp[i], bones[:], sqb[:], True, True, ["bones", "sqb"], [f"pp{i}"])
            P.act(t["lnv"][:], pp[i], AF.Ln, [f"pp{i}"], ["lnv"], bias=1e-12)
            P.act(t["rs"][:], t["lnv"][:], AF.Exp, ["lnv"], ["rs"], scale=-0.5)
            P.tt("gpsimd", t["kkn"][:], t["kq"][:], t["rs"][:], ALU.mult, ["kq", "rs"], ["kkn"])
            for d in range(2):
                sw, ag, kd, bb = t[f"sw{d}"], t[f"ag{d}"], t[f"kd{d}"], t[f"b{d}"]
                P.ts("gpsimd", t["fac"][:], ag[:], vec[:, 16, oc:oc + 1], vec[:, 17, oc:oc + 1], ALU.mult, ALU.add, [f"ag{d}", "vec"], ["fac"])
                P.tt("gpsimd", kd[:], t["k"][:], t["fac"][:], ALU.mult, ["k", "fac"], [f"kd{d}"])
                P.tt("gpsimd", bb[:], t["kkn"][:], ag[:], ALU.mult, ["kkn", f"ag{d}"], [f"b{d}"])
                P.op("vector", lambda e, sw=sw: e.tensor_tensor_scan(out=t["L"][:], data0=rmask[:], data1=sw[:], initial=0.0,
                                                                      op0=ALU.mult, op1=ALU.add), [f"sw{d}", "rmask"], ["L"])
                L3 = v3(t["L"][:])
                if d == 0:
                    P.tt("gpsimd", t["Lx"][:], t["L"][:], sw[:], ALU.subtract, ["L", f"sw{d}"], ["Lx"])
                    Li, Lin = t["L"], "L"
                else:
                    P.tt("gpsimd", v3(t["Lx"][:]), L3[:, :, 63:64].broadcast_to([128, 4, 64]), L3, ALU.subtract, ["L"], ["Lx"])
                    P.tt("gpsimd", t["Lb"][:], t["Lx"][:], sw[:], ALU.add, ["Lx", f"sw{d}"], ["Lb"])
                    Li, Lin = t["Lb"], "Lb"
                P.act(t["E1"][:], Li[:], AF.Exp, [Lin], ["E1"], scale=-C0)
                P.act(t["E3"][:], Li[:], AF.Exp, [Lin], ["E3"], scale=C0)
                P.act(t["E2"][:], t["Lx"][:], AF.Exp, ["Lx"], ["E2"], scale=-C0)
                ar5 = AR[d][:].rearrange("p u a (h s) -> p u a h s", h=2)
                kt4 = KT[d][:].rearrange("p u (h s) -> p u h s", h=2)
                bt4 = BT[d][:].rearrange("p u (h s) -> p u h s", h=2)
                for h2 in range(2):
                    sl = slice(h2 * 64, (h2 + 1) * 64)
                    P.stt(ar5[sl, :, 0, h2, :], v3(t["kkn"][sl, :]), -1.0, v3(t["E2"][sl, :]), ALU.mult, ALU.mult, ["kkn", "E2"], [f"AR{q}{d}"])
                    P.tt("gpsimd", ar5[sl, :, 1, h2, :], v3(t["r"][sl, :]), v3(t["E1"][sl, :]), ALU.mult, ["r", "E1"], [f"AR{q}{d}"])
                    P.tt("vector", kt4[sl, :, h2, :], v3(kd[sl, :]), v3(t["E3"][sl, :]), ALU.mult, [f"kd{d}", "E3"], [f"KT{q}{d}"])
                    P.tt("gpsimd", bt4[sl, :, h2, :], v3(bb[sl, :]), v3(t["E3"][sl, :]), ALU.mult, [f"b{d}", "E3"], [f"BT{q}{d}"])
                E13 = v3(t["E1"][:])
                gsrc = E13[:, :, 63] if d == 0 else E13[:, :, 0]
                P.cp("vector", gam[d][:], gsrc, ["E1"], [f"gam{z}{d}"])
                if d == 1:
                    P.cp("gpsimd", gamb_t[:, oc, :], gam[1][:], [f"gam{z}1"], ["gamb_t"])
                yield
            P.tt("gpsimd", t["ks"][:], t["kd0"][:], t["kd1"][:], ALU.add, ["kd0", "kd1"], ["ks"])
            for h2 in range(2):
                sl = slice(h2 * 64, (h2 + 1) * 64)
                P.stt(RK[sl, :, h2, :], v3(t["r"][sl, :]), vec[sl, 18, oc:oc + 1], v3(t["ks"][sl, :]), ALU.mult, ALU.mult, ["r", "ks", "vec"], ["RK"])
            i = nxt("pp")
            for u in range(4):
                P.mm(pp[i][:, u:u + 1], RK[:, u, :, :].rearrange("p h s -> p (h s)"), onesb[:, 0:1], True, True, ["RK", "onesb"], [f"pp{i}"])
            P.cp("scalar", bon_t[:, oc, :], pp[i][:, 0:4], [f"pp{i}"], ["bon_t"])
            if oc == 7:
                P.dma("sync", S["gamb"][tg], gamb_t[:].rearrange("p a b -> p (a b)"), reads=["gamb_t"], writes=[("gamb", tg)], sem="gamb_t")
                P.dma("sync", S["bon"][tg], bon_t[:].rearrange("p a b -> p (a b)"), reads=["bon_t"], writes=[("bon", tg)], sem="bon_t")
            yield

        def chain(ti, oc, q, d, z):
            AR, KT, BT, Vb = ARq[q][d], KTq[q][d], BTq[q][d], Vbq[z]
            ARn, KTn, BTn, Vbn = f"AR{q}{d}", f"KT{q}{d}", f"BT{q}{d}", f"Vb{z}"
            iv, fn = inv[d], fin[q][d]
            IR = lambda nm: f"i{d}_{nm}"
            FR = lambda nm: f"f{q}{d}_{nm}"
            mS, mC = (0, 2) if d == 0 else (2, 0)
            mSI = masks[:, mS:mS + 2, :].rearrange("p a b -> p (a b)").unsqueeze(1).broadcast_to([128, 4, 256])
            mCb = masks[:, mC, :].unsqueeze(1).broadcast_to([128, 4, 128])
            idb = identb[:].unsqueeze(1).broadcast_to([128, 4, 128])
            for src, srcn, dst, dstn in ((AR[:, :, 0, :], ARn, iv["Atok"], IR("Atok")), (BT[:], BTn, iv["Btok"], IR("Btok")),
                                         (KT[:], KTn, fn["Ktok"], FR("Ktok"))):
                j = nxt("pb")
                pbt = pb[j][:].bitcast(BF16)
                for u in range(4):
                    P.tr(pbt[:, u * 128:(u + 1) * 128], src[:, u, :], identb[:], [srcn, "identb"], [f"pb{j}"])
                P.cp("scalar", dst[:].rearrange("p u x -> p (u x)"), pbt[:, 0:512], [f"pb{j}"], [dstn])
            mSb = masks[:, mS, :].unsqueeze(1).broadcast_to([128, 4, 128])
            mIb = masks[:, mS + 1, :].unsqueeze(1).broadcast_to([128, 4, 128])

            def two_bank(mm_fn):
                j0, j1 = nxt("pb"), nxt("pb")
                for u in range(4):
                    mm_fn(u, pb[j0][:, u * 128:(u + 1) * 128], f"pb{j0}", pb[j1][:, u * 128:(u + 1) * 128], f"pb{j1}")
                return j0, j1

            for lhs, lhsn, dst, dstn in ((BT, BTn, iv["MQ"], IR("MQ")), (KT, KTn, fn["NP"], FR("NP"))):
                def mm_ab(u, o0, n0, o1, n1, lhs=lhs, lhsn=lhsn):
                    P.mm(o0, lhs[:, u, :], AR[:, u, 0, :], True, True, [lhsn, ARn], [n0])
                    P.mm(o1, lhs[:, u, :], AR[:, u, 1, :], True, True, [lhsn, ARn], [n1])
                j0, j1 = two_bank(mm_ab)
                P.tt("vector", dst[:, :, 0:128], u128(pb[j0][:]), mSb, ALU.mult, [f"pb{j0}", "masks"], [dstn])
                P.tt("vector", dst[:, :, 128:256], u128(pb[j1][:]), mIb, ALU.mult, [f"pb{j1}", "masks"], [dstn])
            j = nxt("pb")
            for u in range(4):
                P.mm(pb[j][:, u * 128:(u + 1) * 128], AR[:, u, 0, :], BT[:, u, :], True, True, [ARn, BTn], [f"pb{j}"])
            cur, curn, nx, nxn = iv["MWa"], IR("MWa"), iv["MWb"], IR("MWb")
            P.tt("vector", cur[:, :, 0, :], u128(pb[j][:]), mCb, ALU.mult, [f"pb{j}", "masks"], [curn])
            yield
            j = nxt("pb")
            for u in range(4):
                P.mm(pb[j][:, u * 128:(u + 1) * 128], iv["MQ"][:, u, 0:128], cur[:, u, 0, :], True, True, [IR("MQ"), curn], [f"pb{j}"])
            P.cp("scalar", nx[:, :, 0, :], u128(pb[j][:]), [f"pb{j}"], [nxn])
            P.tt("gpsimd", nx[:, :, 1, :], cur[:, :, 0, :], idb, ALU.add, [curn, "identb"], [nxn])
            j = nxt("pb")
            for u in range(4):
                P.mm(pb[j][:, u * 128:(u + 1) * 128], cur[:, u, 0, :], iv["MQ"][:, u, 0:128], True, True, [IR("MQ"), curn], [f"pb{j}"])
            curT, curTn, nxT, nxTn = iv["MTa"], IR("MTa"), iv["MTb"], IR("MTb")
            P.cp("scalar", curT[:], u128(pb[j][:]), [f"pb{j}"], [curTn])
            cur, curn, nx, nxn = nx, nxn, cur, curn
            yield
            for lev in range(1, 5):
                def mm_lev(u, o0, n0, o1, n1, cur=cur, curn=curn, curT=curT, curTn=curTn):
                    P.mm(o0, curT[:, u, :], cur[:, u, 0, :], True, True, [curTn, curn], [n0])
                    P.mm(o1, curT[:, u, :], cur[:, u, 1, :], True, True, [curTn, curn], [n1])
                j0, j1 = two_bank(mm_lev)
                P.cp("scalar", nx[:, :, 0, :], u128(pb[j0][:]), [f"pb{j0}"], [nxn])
                P.tt("vector", nx[:, :, 1, :], u128(pb[j1][:]), cur[:, :, 1, :], ALU.add, [f"pb{j1}", curn], [nxn])
                j = nxt("pb")
                for u in range(4):
                    P.mm(pb[j][:, u * 128:(u + 1) * 128], cur[:, u, 0, :], curT[:, u, :], True, True, [curn, curTn], [f"pb{j}"])
                P.cp("scalar", nxT[:], u128(pb[j][:]), [f"pb{j}"], [nxTn])
                cur, curn, nx, nxn = nx, nxn, cur, curn
                curT, curTn, nxT, nxTn = nxT, nxTn, curT, curTn
                yield
            j = nxt("pb")
            for u in range(4):
                P.mm(pb[j][:, u * 128:(u + 1) * 128], curT[:, u, :], cur[:, u, 1, :], True, True, [curTn, curn], [f"pb{j}"])
            P.tt("vector", nx[:, :, 1, :], u128(pb[j][:]), cur[:, :, 1, :], ALU.add, [f"pb{j}", curn], [nxn])
            W6, W6n = nx, nxn
            j = nxt("pb")
            for u in range(4):
                P.mm(pb[j][:, u * 64:(u + 1) * 64], fn["NP"][:, u, 0:128], Vb[:, u, :], True, True, [FR("NP"), Vbn], [f"pb{j}"])
            P.cp("scalar", fn["NVb"][:].rearrange("p u x -> p (u x)"), pb[j][:, 0:256], [f"pb{j}"], [FR("NVb")])
            yield

            def mm_d(u, o0, n0, o1, n1):
                P.mm(o0, W6[:, u, 1, :], iv["MQ"][:, u, 128:256], True, True, [W6n, IR("MQ")], [n0])
                P.mm(o1, W6[:, u, 1, :], iv["Btok"][:, u, :], True, True, [W6n, IR("Btok")], [n1])
            j0, j1 = two_bank(mm_d)
            P.cp("scalar", fn["XW"][:, :, 0:128], u128(pb[j0][:]), [f"pb{j0}"], [FR("XW")])
            P.cp("vector", fn["XW"][:, :, 128:256], u128(pb[j1][:]), [f"pb{j1}"], [FR("XW")])
            yield

            def mm_f(u, o0, n0, o1, n1):
                P.mm(o0, iv["Atok"][:, u, :], fn["XW"][:, u, 0:128], True, True, [IR("Atok"), FR("XW")], [n0])
                P.mm(o1, iv["Atok"][:, u, :], fn["XW"][:, u, 128:256], True, True, [IR("Atok"), FR("XW")], [n1])
            j0, j1 = two_bank(mm_f)
            P.tt("vector", fn["GY"][:], u128(pb[j0][:]), AR[:, :, 1, :], ALU.add, [f"pb{j0}", ARn], [FR("GY")])
            P.tt("vector", fn["GS"][:], u128(pb[j1][:]), idb, ALU.add, [f"pb{j1}", "identb"], [FR("GS")])
            yield

        def finish(ti, oc, q, z):
            isctx, idx = RW_ORDER1[ti]
            tg = 16 if isctx else idx
            sf, sb_ = fin[q]
            F0 = lambda nm: f"f{q}0_{nm}"
            F1 = lambda nm: f"f{q}1_{nm}"
            Vb, Vbn, gam = Vbq[z], f"Vb{z}", gamq[z]
            SFR = ("Sf", oc)
            for u in range(4):
                yo = pf[:, u * 64:(u + 1) * 64]
                P.mm(yo, sf["NP"][:, u, 128:256], Vb[:, u, :], True, False, [F0("NP"), Vbn], ["pf"])
                P.mm(yo, sf["XW"][:, u, 0:128], sf["NVb"][:, u, :], False, False, [F0("XW"), F0("NVb")], ["pf"])
                P.mm(yo, sb_["NP"][:, u, 128:256], Vb[:, u, :], False, False, [F1("NP"), Vbn], ["pf"])
                P.mm(yo, sb_["XW"][:, u, 0:128], sb_["NVb"][:, u, :], False, False, [F1("XW"), F1("NVb")], ["pf"])
                P.mm(yo, sf["GY"][:, u, :], Sf[:, oc, :], False, True, [F0("GY"), SFR], ["pf"])
                so = pf[:, 256:320]
                P.mm(so, sf["Ktok"][:, u, :], Vb[:, u, :], True, False, [F0("Ktok"), Vbn], ["pf"])
                P.mm(so, sf["XW"][:, u, 128:256], sf["NVb"][:, u, :], False, False, [F0("XW"), F0("NVb")], ["pf"])
                P.mm(so, sf["GS"][:, u, :], Sf[:, oc, :], False, True, [F0("GS"), SFR], ["pf"])
                P.ts("vector", Sf[:, oc, :], so, gam[0][:, u:u + 1], None, ALU.mult, None, ["pf", f"gam{z}0"], [SFR])
                yield
            P.cp("vector", YPs[:].rearrange("p u x -> p (u x)"), pf[:, 0:256], ["pf"], ["YPs"])
            P.dma("sync", S["yp"][tg, oc], YPs[:].rearrange("p u x -> p (u x)"), reads=["YPs"], writes=[("yp", tg, oc)], sem="YPs")
            j = nxt("pb")
            for u in range(4):
                so = pb[j][:, u * 64:(u + 1) * 64]
                P.mm(so, sb_["Ktok"][:, u, :], Vb[:, u, :], True, False, [F1("Ktok"), Vbn], [f"pb{j}"])
                P.mm(so, sb_["XW"][:, u, 128:256], sb_["NVb"][:, u, :], False, True, [F1("XW"), F1("NVb")], [f"pb{j}"])
            P.cp("scalar", SAs[:].rearrange("p u x -> p (u x)"), pb[j][:, 0:256], [f"pb{j}"], ["SAs"])
            P.dma("sync", S["sadd"][tg, oc], SAs[:].rearrange("p u x -> p (u x)"), reads=["SAs"], writes=[("sadd", tg, oc)], sem="SAs")
            P.dma("sync", S["gyb"][tg, oc], sb_["GY"][:].rearrange("p u x -> p (u x)"), reads=[F1("GY")], writes=[("gyb", tg, oc)], sem=F1("GY"))
            P.dma("sync", S["gsb"][tg, oc], sb_["GS"][:].rearrange("p u x -> p (u x)"), reads=[F1("GS")], writes=[("gsb", tg, oc)], sem=F1("GS"))
            yield

        NT = len(RW_ORDER1)
        NJ = NT * 8
        done = {"prep": set(), "c0": set(), "c1": set(), "fin": set(), "tprep": set()}

        def stream_P():
            for ti in range(NT):
                yield ("tprep", ti, lambda ti=ti: (ti == 0 or ("prep", (ti - 1) * 8 + 7) in donef), lambda ti=ti: tprep(ti))
                for oc in range(8):
                    k = ti * 8 + oc
                    yield ("prep", k, lambda k=k: ((k < 2 or (("c0", k - 2) in donef and ("c1", k - 2) in donef)) and (k < 3 or ("fin", k - 3) in donef)),
                           lambda ti=ti, oc=oc, k=k: prep(ti, oc, k % 2, k % 3))

        def stream_C(d):
            for k in range(NJ):
                ti, oc = divmod(k, 8)
                yield (f"c{d}", k, lambda k=k: (("prep", k) in donef and (k < 2 or ("fin", k - 2) in donef)),
                       lambda ti=ti, oc=oc, k=k: chain(ti, oc, k % 2, d, k % 3))

        def stream_F():
            for k in range(NJ):
                ti, oc = divmod(k, 8)
                yield ("fin", k, lambda k=k: (("c0", k) in donef and ("c1", k) in donef),
                       lambda ti=ti, oc=oc, k=k: finish(ti, oc, k % 2, k % 3))

        donef = set()
        load_hh(0)
        streams = [stream_C(0), stream_C(1), stream_F(), stream_P()]
        cur = [None] * 4
        pend = [None] * 4
        alive = [True] * 4
        while any(alive):
            progressed = False
            for si in range(4):
                if not alive[si]:
                    continue
                if cur[si] is None:
                    if pend[si] is None:
                        try:
                            pend[si] = next(streams[si])
                        except StopIteration:
                            alive[si] = False
                            continue
                    kind, k, ready, mk = pend[si]
                    if not ready():
                        continue
                    cur[si] = (kind, k, mk())
                    pend[si] = None
                kind, k, gen = cur[si]
                try:
                    next(gen)
                    progressed = True
                except StopIteration:
                    donef.add((kind, k))
                    cur[si] = None
                    progressed = True
            assert progressed or not any(alive), "scheduler stuck"


def stage_rwkv2(P, io, G, S, src, xa):
    vec, identb = G["vec"], G["identb"]
    GN_EPS = 64e-5
    with P.phase("rwkv2"):
        wo = P.sb([64, 16, 1024], BF16)
        P.dma("gpsimd", wo[:], io["rwkv_wo"].rearrange("(h v) f -> v h f", v=64), writes=["wo"], sem="wo")
        lnw = P.sb([128, 8, 64], F32)
        lnb = P.sb([128, 8, 64], F32)
        P.dma("sync", lnw[:], io["lnw_st"], writes=["lnw"], sem="lnw")
        P.dma("sync", lnb[:], io["lnb_st"], writes=["lnb"], sem="lnb")
        big = {}
        for nm in ("yp", "sadd", "vst", "gst"):
            big[nm] = [P.sb([128, 8, 256], F32, f"l_{nm}{b}") for b in range(2)]
        for nm in ("gyb", "gsb"):
            big[nm] = [P.sb([128, 8, 512], BF16, f"l_{nm}{b}") for b in range(2)]
        gamb = [P.sb([128, 8, 4], F32) for _ in range(2)]
        bon = [P.sb([128, 8, 4], F32) for _ in range(2)]
        xt = [P.sb([128, 8, 256], F32) for _ in range(2)]
        Sb = P.sb([128, 8, 64], BF16)
        ysb = P.sb([128, 8, 64], F32)
        ysq = P.sb([128, 8, 64], F32)
        tmpS = P.sb([128, 8, 64], F32)
        yn = P.sb([128, 8, 64], F32)
        bv = P.sb([128, 8, 64], F32)
        ob = P.sb([128, 8, 64], BF16)
        st = {nm: P.sb([128, 8], F32, "g_" + nm) for nm in ("s1", "s2", "mean", "msq", "var", "lnv", "rstd")}
        OT = P.sb([64, 16, 256], BF16)
        py = [P.ps([128, 512], F32) for _ in range(2)]
        pS = P.ps([128, 512], F32)
        ptr = P.ps([128, 1024], F32)
        pw = [P.ps([128, 512], F32) for _ in range(2)]
        P.memset("gpsimd", Sb[:], 0.0, ["Sb"])

        def load(k):
            isctx, idx = RW_ORDER2[k]
            tg = 16 if isctx else idx
            b = k % 2
            for nm in ("yp", "sadd", "vst", "gst", "gyb", "gsb"):
                P.dma("sync", big[nm][b][:], S[nm][tg].rearrange("o p x -> p o x"), writes=[f"{nm}{b}"], sem=f"{nm}{b}")
            P.dma("sync", gamb[b][:].rearrange("p a b -> p (a b)"), S["gamb"][tg], writes=[f"gamb{b}"], sem=f"gamb{b}")
            P.dma("sync", bon[b][:].rearrange("p a b -> p (a b)"), S["bon"][tg], writes=[f"bon{b}"], sem=f"bon{b}")
            c0 = T if isctx else idx * 256
            P.dma("sync", xt[b][:], fm(src[:, c0:c0 + 256]), writes=[f"xt{b}"], sem=f"xt{b}")

        load(0)
        for k, (isctx, idx) in enumerate(RW_ORDER2):
            b = k % 2
            if k + 1 < len(RW_ORDER2):
                load(k + 1)
            c0 = T if isctx else idx * 256
            _, _, gates = mod_scalars(G, 0, 0, isctx)
            bc = lambda ap: ap.unsqueeze(2).broadcast_to([128, 8, 64])
            def chain_part(u):
                us = slice(u * 64, (u + 1) * 64)
                q_ = u % 2
                for oc in range(8):
                    P.mm(py[q_][:, oc * 64:(oc + 1) * 64], big["gyb"][b][:, oc, u * 128:(u + 1) * 128], Sb[:, oc, :], True, True, [f"gyb{b}", "Sb"], [f"py{q_}"])
                for oc in range(8):
                    P.mm(pS[:, oc * 64:(oc + 1) * 64], big["gsb"][b][:, oc, u * 128:(u + 1) * 128], Sb[:, oc, :], True, True, [f"gsb{b}", "Sb"], ["pS"])
                pS3 = pS[:].rearrange("p (o v) -> p o v", v=64)
                P.tt("vector", tmpS[:], pS3, big["sadd"][b][:, :, us], ALU.add, ["pS", f"sadd{b}"], ["tmpS"])
                P.tt("vector", Sb[:], tmpS[:], bc(gamb[b][:, :, u]), ALU.mult, ["tmpS", f"gamb{b}"], ["Sb"])

            def read_part(u):
                us = slice(u * 64, (u + 1) * 64)
                q_ = u % 2
                py3 = py[q_][:].rearrange("p (o v) -> p o v", v=64)
                P.tt("vector", ysb[:], py3, big["yp"][b][:, :, us], ALU.add, [f"py{q_}", f"yp{b}"], ["ysb"])
                P.op("vector", lambda e: e.tensor_reduce(out=st["s1"][:], in_=ysb[:], axis=AX.X, op=ALU.add), ["ysb"], ["s1"])
                P.tt("gpsimd", ysq[:], ysb[:], ysb[:], ALU.mult, ["ysb"], ["ysq"])
                P.op("vector", lambda e: e.tensor_reduce(out=st["s2"][:], in_=ysq[:], axis=AX.X, op=ALU.add), ["ysq"], ["s2"])
                P.ts("vector", st["mean"][:], st["s1"][:], 1.0 / 64, None, ALU.mult, None, ["s1"], ["mean"])
                P.tt("vector", st["msq"][:], st["mean"][:], st["mean"][:], ALU.mult, ["mean"], ["msq"])
                P.stt(st["var"][:], st["s2"][:], 1.0 / 64, st["msq"][:], ALU.mult, ALU.subtract, ["s2", "msq"], ["var"])
                P.act(st["lnv"][:], st["var"][:], AF.Ln, ["var"], ["lnv"], bias=GN_EPS)
                P.act(st["rstd"][:], st["lnv"][:], AF.Exp, ["lnv"], ["rstd"], scale=-0.5)
                P.tt("gpsimd", yn[:], ysb[:], bc(st["mean"][:]), ALU.subtract, ["ysb", "mean"], ["yn"])
                P.tt("gpsimd", bv[:], big["vst"][b][:, :, us], bc(bon[b][:, :, u]), ALU.mult, [f"vst{b}", f"bon{b}"], ["bv"])
                P.tt("vector", yn[:], yn[:], bc(st["rstd"][:]), ALU.mult, ["yn", "rstd"], ["yn"])
                P.tt("gpsimd", yn[:], yn[:], lnw[:], ALU.mult, ["yn", "lnw"], ["yn"])
                P.tt("vector", yn[:], yn[:], lnb[:], ALU.add, ["yn", "lnb"], ["yn"])
                P.tt("gpsimd", yn[:], yn[:], bv[:], ALU.add, ["yn", "bv"], ["yn"])
                P.tt("vector", ob[:], yn[:], big["gst"][b][:, :, us], ALU.mult, ["yn", f"gst{b}"], ["ob"])
                ptb = ptr[:].bitcast(BF16)
                for oc in range(8):
                    P.tr(ptb[0:64, oc * 128:(oc + 1) * 128], ob[:, oc, :], identb[:], ["ob", "identb"], ["ptr"])
                P.cp("scalar", OT[:, :, us], ptb[0:64, 0:1024].rearrange("p (h t) -> p h t", t=64), ["ptr"], ["OT"])

            chain_part(3)
            for u in range(3, -1, -1):
                if u > 0:
                    chain_part(u - 1)
                read_part(u)
            for oc in range(8):
                j = oc % 2
                for h in range(16):
                    P.mm(pw[j][:, 0:256], wo[:, h, oc * 128:(oc + 1) * 128], OT[:, h, :], h == 0, h == 15, ["wo", "OT"], [f"pw{j}"])
                P.stt(xt[b][:, oc, :], pw[j][:, 0:256], gates[oc], xt[b][:, oc, :], ALU.mult, ALU.add, [f"pw{j}", f"xt{b}", "modv"], [f"xt{b}"])
            P.dma("sync", fm(xa[:, c0:c0 + 256]), xt[b][:], reads=[f"xt{b}"], writes=[("xa", k)], sem=f"xt{b}")


def stage_qkv(P, io, G, hb, qtd, Kz, VA):
    vec, bones, perm = G["vec"], G["bones"], G["perm"]
    with P.phase("qkv"):
        wq = P.sb([128, 8, 1024], BF16)
        wkd = P.sb([128, 8, 512], BF16)
        wv = P.sb([128, 8, 256], BF16)
        P.dma("gpsimd", wq[:], fm(io["attn_wq"]), writes=["wq"], sem="wq")
        P.dma("gpsimd", wkd[:], fm(io["attn_wkd"]), writes=["wkd"], sem="wkd")
        P.dma("gpsimd", wv[:], fm(io["attn_wv"]), writes=["wv"], sem="wv")
        ht = [P.sb([128, 8, 512], BF16) for _ in range(2)]
        cs = [P.sb([128, 512], F32) for _ in range(2)]
        sn = [P.sb([128, 512], F32) for _ in range(2)]
        NB = 2
        qf = [P.sb([128, 512], F32) for _ in range(NB)]
        sqb = [P.sb([128, 512], BF16) for _ in range(NB)]
        lnv = [P.sb([128, 512], F32) for _ in range(NB)]
        rstd = [P.sb([128, 512], F32) for _ in range(NB)]
        qh = [P.sb([128, 512], F32) for _ in range(NB)]
        qhb = [P.sb([128, 512], BF16) for _ in range(NB)]
        t1 = [P.sb([128, 512], F32) for _ in range(NB)]
        t2 = [P.sb([128, 512], F32) for _ in range(NB)]
        qst = [P.sb([128, 8, 512], BF16) for _ in range(2)]
        pp = [P.ps([128, 512], F32) for _ in range(6)]
        cnt = [0, 0]

        def nxt():
            cnt[0] += 1
            return cnt[0] % 6

        P.memset("gpsimd", VA[:], 0.0, ["VA0"])
        P.memset("gpsimd", VA[:].rearrange("p k (j x) -> p k j x", x=65)[:, :, 0:5, 64:65], 1.0, ["VA0"])
        P.memset("gpsimd", Kz[0][64:128, :, :], 0.0, ["Kz0z"])
        P.memset("gpsimd", Kz[1][0:64, :, :], 0.0, ["Kz1z"])
        tiles = ALL_TILES

        def load(i):
            c0, tw, isctx = tiles[i]
            b = i % 2
            P.dma("sync", ht[b][:, :, :tw], fm(hb[:, c0:c0 + tw]), writes=[f"ht{b}"], sem=f"ht{b}")
            if not isctx:
                P.dma("sync", cs[b][:, :tw], io["cosT"][:, c0:c0 + tw], writes=[f"cs{b}"], sem=f"cs{b}")
                P.dma("sync", sn[b][:, :tw], io["sinT"][:, c0:c0 + tw], writes=[f"sn{b}"], sem=f"sn{b}")

        def normrope(wcols, nscal, dsts, b, tw, isctx, wname, dres="dstqk"):
            cnt[1] += 1
            n = cnt[1] % NB
            i = nxt()
            for c in range(8):
                P.mm(pp[i][:, :tw], wcols(c), ht[b][:, c, :tw], c == 0, c == 7, [wname, f"ht{b}"], [f"pp{i}"])
            P.cp("scalar", qf[n][:, :tw], pp[i][:, :tw], [f"pp{i}"], [f"qf{n}"])
            P.act(sqb[n][:, :tw], qf[n][:, :tw], AF.Square, [f"qf{n}"], [f"sqb{n}"])
            yield
            i = nxt()
            P.mm(pp[i][:, :tw], bones[:], sqb[n][:, :tw], True, True, ["bones", f"sqb{n}"], [f"pp{i}"])
            P.act(lnv[n][:, :tw], pp[i][:, :tw], AF.Ln, [f"pp{i}"], [f"lnv{n}"], bias=1e-6, scale=1.0 / 64)
            P.act(rstd[n][:, :tw], lnv[n][:, :tw], AF.Exp, [f"lnv{n}"], [f"rstd{n}"], scale=-0.5)
            yield
            P.stt(qh[n][:, :tw], qf[n][:, :tw], nscal, rstd[n][:, :tw], ALU.mult, ALU.mult, [f"qf{n}", f"rstd{n}", "vec"], [f"qh{n}"])
            if isctx:
                for dst, sl in dsts:
                    P.cp("gpsimd", dst, qh[n][sl, :tw], [f"qh{n}"], [dres])
                return
            P.cp("gpsimd", qhb[n][:, :tw], qh[n][:, :tw], [f"qh{n}"], [f"qhb{n}"])
            yield
            i = nxt()
            P.mm(pp[i][:, :tw], perm[:], qhb[n][:, :tw], True, True, ["perm", f"qhb{n}"], [f"pp{i}"])
            P.tt("gpsimd", t1[n][:, :tw], qh[n][:, :tw], cs[b][:, :tw], ALU.mult, [f"qh{n}", f"cs{b}"], [f"t1{n}"])
            P.tt("vector", t2[n][:, :tw], pp[i][:, :tw], sn[b][:, :tw], ALU.mult, [f"pp{i}", f"sn{b}"], [f"t2{n}"])
            yield
            for dst, sl in dsts:
                P.tt("gpsimd", dst, t1[n][sl, :tw], t2[n][sl, :tw], ALU.add, [f"t1{n}", f"t2{n}"], [dres])

        ALLP = slice(0, 128)
        load(0)
        for i, (c0, tw, isctx) in enumerate(tiles):
            b = i % 2
            if i + 1 < len(tiles):
                load(i + 1)
            jobs = []
            if not isctx:
                for oc in range(8):
                    jobs.append(normrope(lambda c, oc=oc: wq[:, c, oc * 128:(oc + 1) * 128], vec[:, 19, oc:oc + 1], [(qst[b][:, oc, :tw], ALLP)], b, tw, False, "wq",
                                         dres=(f"qst{b}", oc)))
            for g in range(4):
                jobs.append(normrope(lambda c, g=g: wkd[:, c, g * 128:(g + 1) * 128], vec[:, 20, 0:1],
                                     [(Kz[0][0:64, g, c0:c0 + tw], slice(0, 64)), (Kz[1][64:128, g, c0:c0 + tw], slice(64, 128))], b, tw, isctx, "wkd"))

            def vjob():
                for sub in range(tw // 128):
                    kt = c0 // 128 + sub
                    j = nxt()
                    for c in range(8):
                        P.mm(pp[j][:, 0:256], ht[b][:, c, sub * 128:(sub + 1) * 128], wv[:, c, :], c == 0, c == 7, ["wv", f"ht{b}"], [f"pp{j}"])
                    P.cp("scalar", VA[:, kt, 65:325].rearrange("p (g x) -> p g x", x=65)[:, :, 0:64],
                         pp[j][:, 0:256].rearrange("p (g d) -> p g d", d=64), [f"pp{j}", "VA0"], [("VA", kt)])
                    yield

            jobs.append(vjob())
            active = []
            while jobs or active:
                while jobs and len(active) < 2:
                    active.append(jobs.pop(0))
                for gen in list(active):
                    try:
                        next(gen)
                    except StopIteration:
                        active.remove(gen)
            if not isctx:
                P.dma("sync", fm(qtd[:, c0:c0 + tw]), qst[b][:, :, :tw], reads=[(f"qst{b}", oc) for oc in range(8)], writes=[("qtd", i)], sem=f"qst{b}")


def stage_attn(P, io, G, qtd, Kz, VA, xa):
    with P.phase("attn"):
        wo = P.sb([128, 8, 1024], BF16)
        P.dma("gpsimd", wo[:], fm(io["attn_wo"]), writes=["wo"], sem="wo")
        sel = P.sb([128, 2, 128], F32)
        P.dma("sync", sel[:], io["c_sel"], writes=["sel"], sem="sel")
        PT = [P.sb([128, 1024], BF16) for _ in range(3)]
        osb = [P.sb([128, 512], F32) for _ in range(2)]
        rb = [P.sb([128, 512], F32) for _ in range(2)]
        xt = P.sb([128, 8, 512], F32)
        QB = [P.sb([128, 8, 512], BF16) for _ in range(2)]
        psS = [P.ps([128, 1024], F32) for _ in range(2)]
        psO = [P.ps([128, 512], F32) for _ in range(2)]
        psB = P.ps([128, 512], F32)
        pX = [P.ps([128, 512], F32) for _ in range(1)]
        _, _, gates = mod_scalars(G, 1, 0, False)
        for k in range(2):
            P.memset("gpsimd", osb[k][:], 0.0, [f"osb{k}"])
        def loadq(qb):
            P.dma("sync", QB[qb % 2][:], fm(qtd[:, qb * 512:(qb + 1) * 512]), writes=[("QT", h, qb) for h in range(16)], sem=f"QB{qb % 2}")

        loadq(0)
        for qb in range(8):
            qsl = slice(qb * 512, (qb + 1) * 512)
            QT = QB[qb % 2]
            if qb + 1 < 8:
                loadq(qb + 1)
            P.dma("sync", xt[:], fm(xa[:, qsl]), writes=["xt"], sem="xt")
            steps = [(h, kp) for h in range(16) for kp in range(17)]

            def S(i):
                h, kp = steps[i]
                g, oc, h2 = h // 4, h // 2, h % 2
                for e_ in range(2):
                    kt = 2 * kp + e_
                    P.mm(psS[i % 2][:, e_ * 512:(e_ + 1) * 512], Kz[h2][:, g, kt * 128:(kt + 1) * 128], QT[:, oc, :], True, True,
                         ["Kz", ("QT", 2 * oc, qb), ("QT", 2 * oc + 1, qb)], [f"psS{i % 2}"])

            def epi_a(h):
                o = h % 2
                P.cp("vector", osb[o][:], psO[o][:], [f"psO{o}"], [f"osb{o}"])

            def epi_b(h):
                oc, h2, o = h // 2, h % 2, h % 2
                hs = slice(h2 * 64, h2 * 64 + 64)
                P.mm(psB[:, :], sel[:, h2, :], osb[o][:], True, True, ["sel", f"osb{o}"], ["psB"])
                P.act(rb[o][hs, :], psB[hs, :], AF.Ln, ["psB"], [f"rb{o}"])
                P.act(rb[o][hs, :], rb[o][hs, :], AF.Exp, [f"rb{o}"], [f"rb{o}"], scale=-1.0)
                P.tt("gpsimd", QT[hs, oc, :], osb[o][hs, :], rb[o][hs, :], ALU.mult, [f"osb{o}", f"rb{o}"], [("QT", h, qb)])

            S(0)
            pend = {}
            for i, (h, kp) in enumerate(steps):
                g, h2, o = h // 4, h % 2, h % 2
                if i + 1 < len(steps):
                    S(i + 1)
                p_ = i % 3
                P.act(PT[p_][:], psS[i % 2][:, :], AF.Exp, [f"psS{i % 2}"], [f"PT{p_}"], scale=0.125)
                v0 = 65 + 65 * g if h2 == 0 else 1 + 65 * g
                for e_ in range(2):
                    kt = 2 * kp + e_
                    P.mm(psO[o][:, :], VA[:, kt, v0:v0 + 128], PT[p_][:, e_ * 512:(e_ + 1) * 512], kt == 0, kt == 33, [f"PT{p_}", "VA"], [f"psO{o}"])
                if kp == 16:
                    epi_a(h)
                    pend[i + 3] = h
                if i in pend:
                    epi_b(pend.pop(i))
            for k in sorted(pend):
                epi_b(pend[k])
            for oc in range(8):
                j = 0
                for c in range(8):
                    P.mm(pX[j][:, :], wo[:, c, oc * 128:(oc + 1) * 128], QT[:, c, :], c == 0, c == 7,
                         ["wo", ("QT", 2 * c, qb), ("QT", 2 * c + 1, qb)], [f"pX{j}"])
                P.stt(xt[:, oc, :], pX[j][:, :], gates[oc], xt[:, oc, :], ALU.mult, ALU.add, [f"pX{j}", "xt", "modv"], ["xt"])
            P.dma("sync", fm(xa[:, qsl]), xt[:], reads=["xt"], writes=[("xa", qb)], sem="xt")


IN_SHAPES = {
    "xin": [D, TT], "cvec": [128, 8, 2], "w_mod": [2, D, 6 * D], "b_mod": [2, 6 * D], "vecs": [128, NV, 8],
    "mlp_w1": [2, D, 4 * D], "mlp_w2": [2, 4 * D, D],
    "rwkv_wr": [D, D], "rwkv_wk": [D, D], "rwkv_wv": [D, D], "rwkv_wo": [D, D],
    "rwkv_w1": [2, D, 64], "rwkv_w2": [2, 64, D], "rwkv_a1": [2, D, 64], "rwkv_a2": [2, 64, D],
    "rwkv_g1": [D, 128], "rwkv_g2": [128, D], "lnw_st": [128, 8, 64], "lnb_st": [128, 8, 64],
    "attn_wq": [D, D], "attn_wkd": [D, 512], "attn_wv": [D, 256], "attn_wo": [D, D],
    "cosT": [128, T], "sinT": [128, T],
    "c_ident": [128, 128], "c_ones": [128, 128], "c_bones": [128, 128], "c_masks": [128, 4, 128],
    "c_perm": [128, 128], "c_rmask": [128, 256], "c_sel": [128, 2, 128],
}


class IO(dict):
    def __init__(self, nc):
        super().__init__()
        self.nc = nc
        self.used = []

    def __missing__(self, k):
        ap = self.nc.dram_tensor(k, IN_SHAPES[k], F32, kind="ExternalInput").ap()
        self[k] = ap
        self.used.append(k)
        return ap

    def scratch(self, name, shape, dtype):
        return self.nc.dram_tensor(name, list(shape), dtype, kind="Internal").ap()

    def output(self, name, shape, dtype=F32):
        return self.nc.dram_tensor(name, list(shape), dtype, kind="ExternalOutput").ap()


def build(stages="all", dbg=None):
    nc = bass.Bass("TRN2", target_bir_lowering=False)
    io = IO(nc)
    P = Prog(nc)
    G = {}
    outs = {}
    stage_init(P, io, G)
    xa = io.scratch("xa", [D, TT], F32)
    hb = io.scratch("hb", [D, TT], BF16)
    if stages == "t_mlp":
        outs["dbg_h"] = io.output("dbg_h", [D, TT], BF16)
        stage_norm(P, io, G, "n_t", io["xin"], ALL_TILES,
                   lambda ic: mod_scalars(G, 0, 1, ic)[0], lambda ic: mod_scalars(G, 0, 1, ic)[1],
                   lambda c0, tw, ic: fm(hb[:, c0:c0 + tw]), BF16)
        with P.phase("copy"):
            P.dma("sync", xa, io["xin"], writes=["xa"], sem="cpa")
            P.dma("sync", outs["dbg_h"], hb, writes=["o"], sem="cpb")
        stage_mlp(P, io, G, 0, ALL_TILES, xa, hb)
        outs["y"] = io.output("y", [D, TT])
        fin = [G["vec"][:, 4, c:c + 1] for c in range(8)]
        stage_norm(P, io, G, "final", xa, ALL_TILES, lambda ic: fin, lambda ic: None,
                   lambda c0, tw, ic: fm(outs["y"][:, c0:c0 + tw]), F32)
    if stages in ("all", "l0", "l1pre"):
        hp = io.scratch("hp", [D, 4608], F32)
        S = rw_scratch(io)
        with P.phase("zpad"):
            z = P.sb([128, 8, 64], F32)
            P.memset("vector", z[:], 0.0, ["z"])
            for k, o in enumerate((0, 64 + T, 4224, 4288 + C)):
                P.dma("sync", fm(hp[:, o:o + 64]), z[:], reads=["z"], writes=[("hpz", k)], sem=f"z{k}")

        def hdst(c0, tw, ic):
            o = 4288 if ic else 64 + c0
            return fm(hp[:, o:o + tw])

        def hbdst(c0, tw, ic):
            return fm(hb[:, c0:c0 + tw])

        def ms(l, kind, which):
            return lambda ic: mod_scalars(G, l, kind, ic)[which]

        stage_norm(P, io, G, "n_mix0", io["xin"], ALL_TILES, ms(0, 0, 0), ms(0, 0, 1), hdst, F32)
        stage_rwkv1(P, io, G, hp, S)
        stage_rwkv2(P, io, G, S, io["xin"], xa)
        stage_norm(P, io, G, "n_mlp0", xa, ALL_TILES, ms(0, 1, 0), ms(0, 1, 1), hbdst, BF16)
        stage_mlp(P, io, G, 0, ALL_TILES, xa, hb)
        if stages == "l0":
            outs["y"] = io.output("y", [D, TT])
            with P.phase("copyout"):
                P.dma("sync", outs["y"], xa, writes=["o"], sem="cpa")
        else:
            stage_norm(P, io, G, "n_mix1", xa, ALL_TILES, ms(1, 0, 0), ms(1, 0, 1), hbdst, BF16)
            with P.scope():
                QT = io.scratch("qtd", [D, T], BF16)
                Kz = [P.ssb([128, 4, TT], BF16, f"Kz{k}") for k in range(2)]
                VA = P.ssb([128, 34, 390], BF16, "VA")
                stage_qkv(P, io, G, hb, QT, Kz, VA)
                stage_attn(P, io, G, QT, Kz, VA, xa)
            if stages == "l1pre":
                outs["y"] = io.output("y", [D, TT])
                with P.phase("copyout"):
                    P.dma("sync", outs["y"], xa, writes=["o"], sem="cpa")
            else:
                stage_norm(P, io, G, "n_mlp1", xa, LAT_TILES, ms(1, 1, 0), ms(1, 1, 1), hbdst, BF16)
                stage_mlp(P, io, G, 1, LAT_TILES, xa, hb)
                outs["y"] = io.output("y", [D, T])
                fin = [G["vec"][:, 4, c:c + 1] for c in range(8)]
                stage_norm(P, io, G, "final", xa, LAT_TILES, lambda ic: fin, lambda ic: None,
                           lambda c0, tw, ic: fm(outs["y"][:, c0:c0 + tw]), F32)
    if stages == "t_rwkv":
        hp = io.scratch("hp", [D, 4608], F32)
        S = rw_scratch(io)
        with P.phase("zpad"):
            z = P.sb([128, 8, 64], F32)
            P.memset("vector", z[:], 0.0, ["z"])
            for k, o in enumerate((0, 64 + T, 4224, 4288 + C)):
                P.dma("sync", fm(hp[:, o:o + 64]), z[:], reads=["z"], writes=[("hpz", k)], sem=f"z{k}")
        def hdst(c0, tw, ic):
            o = 4288 if ic else 64 + c0
            return fm(hp[:, o:o + tw])
        stage_norm(P, io, G, "n_mix0", io["xin"], ALL_TILES,
                   lambda ic: mod_scalars(G, 0, 0, ic)[0], lambda ic: mod_scalars(G, 0, 0, ic)[1], hdst, F32)
        stage_rwkv1(P, io, G, hp, S)
        stage_rwkv2(P, io, G, S, io["xin"], xa)
        outs["y"] = io.output("y", [D, TT])
        with P.phase("copyout"):
            P.dma("sync", outs["y"], xa, writes=["o"], sem="cpa")
    P.close()
    return nc, io.used, list(outs.keys()), P


def fmv(v):
    return np.ascontiguousarray(np.asarray(v, np.float32).reshape(8, 128).T)


def host_consts():
    c = {}
    c["c_ident"] = np.eye(128, dtype=np.float32)
    c["c_ones"] = np.ones((128, 128), np.float32)
    blk = np.zeros((128, 128), np.float32)
    blk[:64, :64] = 1
    blk[64:, 64:] = 1
    c["c_bones"] = blk
    i = np.arange(64)
    us = (i[:, None] < i[None, :]).astype(np.float32)
    ui = (i[:, None] <= i[None, :]).astype(np.float32)
    m = np.zeros((128, 4, 128), np.float32)
    for k, mk in enumerate([us, ui, us.T, ui.T]):
        m[:64, k, :64] = mk
        m[64:, k, 64:] = mk
    c["c_masks"] = m
    Pm = np.zeros((128, 128), np.float32)
    for d in range(128):
        if d % 32 < 16:
            Pm[d, d + 16] = -1.0
        else:
            Pm[d, d - 16] = 1.0
    c["c_perm"] = np.ascontiguousarray(Pm.T)
    sel = np.zeros((128, 2, 128), np.float32)
    sel[64, 0, :] = 1.0
    sel[63, 1, :] = 1.0
    c["c_sel"] = sel
    rm = np.ones((128, 256), np.float32)
    rm[:, ::64] = 0
    c["c_rmask"] = rm
    t = np.arange(T)
    row = (t // 64).astype(np.float32)
    col = (t % 64).astype(np.float32)
    freqs = (np.float32(10000.0) ** (-np.arange(0, 32, 2, dtype=np.float32) / np.float32(32))).astype(np.float32)
    ang = np.zeros((64, T), np.float32)
    for d in range(64):
        pos = row if d < 32 else col
        ang[d] = pos * freqs[d % 16]
    c["cosT"] = np.ascontiguousarray(np.concatenate([np.cos(ang), np.cos(ang)], 0).astype(np.float32))
    c["sinT"] = np.ascontiguousarray(np.concatenate([np.sin(ang), np.sin(ang)], 0).astype(np.float32))
    return c


def host_inputs(inp, b):
    f = lambda k: np.asarray(inp[k], np.float32)
    d = {}
    d["xin"] = np.ascontiguousarray(np.concatenate([f("x")[b].T, f("ctx")[b].T], axis=1))
    d["cvec"] = np.ascontiguousarray(np.stack([fmv(f("c")[b]), fmv(f("c_ctx"))], axis=-1))
    return d


def host_shared(inp):
    f = lambda k: np.asarray(inp[k], np.float32)
    s = dict(host_consts())
    s["w_mod"] = f("w_mod")
    s["b_mod"] = f("b_mod")
    vl = [f("norm_mix")[0], f("norm_mix")[1], f("norm_mlp")[0], f("norm_mlp")[1], f("final_norm")]
    vl += [f("rwkv_mu")[0, j] for j in range(6)]
    vl += [f("rwkv_w0")[0, 0], f("rwkv_w0")[0, 1], f("rwkv_a0")[0, 0], f("rwkv_a0")[0, 1]]
    vl += [f("rwkv_k_k")[0], f("rwkv_k_a")[0], np.zeros(D, np.float32), f("rwkv_r_k")[0].reshape(-1)]
    vl += [np.tile(f("attn_q_norm")[0], 16), np.tile(f("attn_k_norm")[0], 16)]
    assert len(vl) == NV
    s["vecs"] = np.ascontiguousarray(np.stack([fmv(v) for v in vl], axis=1))
    s["mlp_w1"] = f("mlp_w1")
    s["mlp_w2"] = f("mlp_w2")
    for k in ("wr", "wk", "wv", "wo", "w1", "w2", "a1", "a2", "g1", "g2"):
        s["rwkv_" + k] = f("rwkv_" + k)[0]
    lw = f("rwkv_ln_w")[0].reshape(8, 2, 64)
    lb = f("rwkv_ln_b")[0].reshape(8, 2, 64)
    s["lnw_st"] = np.ascontiguousarray(np.repeat(lw.transpose(1, 0, 2), 64, axis=0))
    s["lnb_st"] = np.ascontiguousarray(np.repeat(lb.transpose(1, 0, 2), 64, axis=0))
    wqkv = f("attn_wqkv")[0]
    s["attn_wq"] = np.ascontiguousarray(wqkv[:, :1024])
    wk = wqkv[:, 1024:1280].reshape(D, 4, 64)
    s["attn_wkd"] = np.ascontiguousarray(np.concatenate([wk, wk], axis=2).reshape(D, 512))
    s["attn_wv"] = np.ascontiguousarray(wqkv[:, 1280:1536])
    s["attn_wo"] = f("attn_wo")[0]
    return s


_CACHE = {}


def kernel(**inputs):
    if "prog" not in _CACHE:
        _CACHE["prog"] = build("all")
    nc, used, outnames, _ = _CACHE["prog"]
    shared = host_shared(inputs)
    in_maps = []
    for b in range(NCORES):
        hi = host_inputs(inputs, b)
        hi.update(shared)
        in_maps.append({k: hi[k] for k in used})
    res = run_bass_kernel_spmd(nc, in_maps, core_ids=list(range(NCORES)))
    out = np.stack([np.ascontiguousarray(res.results[b]["y"].T) for b in range(NCORES)], axis=0)
    return out.astype(np.float32)
```

```python
from contextlib import ExitStack, contextmanager
import re as re_mod
import numpy as np
import concourse.bass as bass
import concourse.mybir as mybir
from concourse.bass_utils import run_bass_kernel_spmd

F32 = mybir.dt.float32
BF16 = mybir.dt.bfloat16
AF = mybir.ActivationFunctionType
ALU = mybir.AluOpType
AX = mybir.AxisListType

D = 1024
T = 4096
C = 256
TT = T + C
NCORES = 8
C0 = float(np.exp(-0.5))
NV = 21
ENGS = ("tensor", "vector", "scalar", "gpsimd", "sync")


class Prog:
    def __init__(self, nc):
        self.nc = nc
        self.ges = ExitStack()
        self.sems = {}
        self.cnt = {}
        self.dpool = {False: [], True: []}
        self.seen = {e: {} for e in ENGS}
        self.n = 0
        self.pes = None
        self.total_ops = 0

    def _alloc(self, es, fn, shape, dtype, name):
        self.n += 1
        return es.enter_context(fn(name or f"t{self.n}", list(shape), dtype))

    def gsb(self, shape, dtype, name=None):
        return self._alloc(self.ges, self.nc.sbuf_tensor, shape, dtype, name)

    def sb(self, shape, dtype, name=None):
        return self._alloc(self.pes, self.nc.sbuf_tensor, shape, dtype, name)

    @contextmanager
    def scope(self):
        self.ses = ExitStack()
        yield self
        self.ses.close()
        self.ses = None

    def ssb(self, shape, dtype, name=None):
        return self._alloc(self.ses, self.nc.sbuf_tensor, shape, dtype, name)

    def ps(self, shape, dtype, name=None):
        return self._alloc(self.pes, self.nc.psum_tensor, shape, dtype, name)

    @contextmanager
    def phase(self, name):
        self.ops = []
        self.last_w = {}
        self.readers = {}
        self.last_dma = {}
        self.pes = ExitStack()
        self.pname = name
        yield self
        self._emit()
        self.pes.close()
        self.pes = None

    _PSUM_RE = re_mod.compile(r"^(pp|pa|pb|pq|pf|ps\w*|pX|py|pS|ptr|pw)\d*$")

    def _deps(self, reads, writes):
        extra = tuple(r for r in reads if isinstance(r, str) and self._PSUM_RE.match(r) and r not in writes)
        if extra:
            writes = tuple(writes) + extra
        deps = {}
        for r in reads:
            if r in self.last_w:
                deps.setdefault(self.last_w[r], set()).add("RAW")
        for w in writes:
            if w in self.last_w:
                deps.setdefault(self.last_w[w], set()).add("WAW")
            for rd in self.readers.get(w, ()):
                deps.setdefault(rd, set()).add("WAR")
        idx = len(self.ops)
        for r in reads:
            self.readers.setdefault(r, []).append(idx)
        for w in writes:
            self.last_w[w] = idx
            self.readers[w] = []
        return deps

    def op(self, eng, fn, reads=(), writes=()):
        deps = self._deps(tuple(reads), tuple(writes))
        self.ops.append(dict(eng=eng, fn=fn, deps=deps, dma=None))
        return len(self.ops) - 1

    def dma(self, queue, out, in_, reads=(), writes=(), sem=None):
        deps = self._deps(tuple(reads), tuple(writes))
        prev = self.last_dma.get(sem)
        if prev is not None:
            deps.setdefault(prev, set()).add("SER")
        idx = len(self.ops)
        self.last_dma[sem] = idx
        self.ops.append(dict(eng=queue, fn=lambda e: e.dma_start(out=out, in_=in_), deps=deps, dma=sem))
        return idx

    def _emit(self):
        nc = self.nc
        ops = self.ops
        if self.last_dma:
            ops.append(dict(eng="sync", fn=None, deps={i: {"FIN"} for i in self.last_dma.values()}, dma=None))
        self.total_ops += len(ops)

        def needs_wait(x, d, kinds):
            if d["dma"] is not None or x["dma"] is not None:
                return True
            if d["eng"] != x["eng"]:
                return True
            if x["eng"] == "tensor":
                return False
            return bool(kinds & {"RAW", "FIN"})

        signal = [False] * len(ops)
        for x in ops:
            for di, kinds in x["deps"].items():
                d = ops[di]
                if d["dma"] is None and needs_wait(x, d, kinds):
                    signal[di] = True
        dkeys = {}
        nk = {False: 0, True: 0}
        for o in ops:
            if o["dma"] is not None and o["dma"] not in dkeys:
                sw = o["eng"] == "gpsimd"
                dkeys[o["dma"]] = (sw, nk[sw])
                nk[sw] += 1
        for sw in (False, True):
            while len(self.dpool[sw]) < nk[sw]:
                h = self.ges.enter_context(nc.semaphore(f"dq{int(sw)}_{len(self.dpool[sw])}"))
                self.dpool[sw].append([h, 0])
        for e in ENGS:
            if e not in self.sems:
                self.sems[e] = self.ges.enter_context(nc.semaphore(f"e_{e}"))
        token = [None] * len(ops)
        for i, o in enumerate(ops):
            if o["dma"] is not None:
                dk = dkeys[o["dma"]]
                slot = self.dpool[dk[0]][dk[1]]
                slot[1] += 16
                token[i] = (("d", dk), slot[1])
            elif signal[i]:
                self.cnt[o["eng"]] = self.cnt.get(o["eng"], 0) + 1
                token[i] = (("e", o["eng"]), self.cnt[o["eng"]])
        per_eng = {e: [] for e in ENGS}
        for i, o in enumerate(ops):
            per_eng[o["eng"]].append(i)

        def semh(key):
            return self.dpool[key[1][0]][key[1][1]][0] if key[0] == "d" else self.sems[key[1]]

        def run(engname, eng):
            seen = self.seen[engname]
            for i in per_eng[engname]:
                o = ops[i]
                waits = {}
                for di, kinds in o["deps"].items():
                    d = ops[di]
                    if not needs_wait(o, d, kinds):
                        continue
                    key, val = token[di]
                    if waits.get(key, 0) < val:
                        waits[key] = val
                for key, val in waits.items():
                    if seen.get(key, 0) >= val:
                        continue
                    seen[key] = val
                    eng.wait_ge(semh(key), val)
                if o["fn"] is None:
                    continue
                ins = o["fn"](eng)
                if o["dma"] is not None:
                    ins.then_inc(semh(token[i][0]), 16)
                elif signal[i]:
                    ins.then_inc(self.sems[engname], 1)

        with nc.Block() as block:
            @block.sync
            def _(e):
                run("sync", e)

            @block.tensor
            def _(e):
                run("tensor", e)

            @block.vector
            def _(e):
                run("vector", e)

            @block.scalar
            def _(e):
                run("scalar", e)

            @block.gpsimd
            def _(e):
                run("gpsimd", e)

    def close(self):
        self.ges.close()

    def mm(self, out, lhsT, rhs, start, stop, r, w):
        self.op("tensor", lambda e: e.matmul(out, lhsT=lhsT, rhs=rhs, start=start, stop=stop), r, w)

    def tr(self, out, in_, ident, r, w):
        self.op("tensor", lambda e: e.transpose(out, in_, ident), r, w)

    def tt(self, eng, out, in0, in1, op, r, w):
        self.op(eng, lambda e: e.tensor_tensor(out=out, in0=in0, in1=in1, op=op), r, w)

    def ts(self, eng, out, in0, s1, s2, op0, op1, r, w):
        if op1 is None:
            self.op(eng, lambda e: e.tensor_scalar(out=out, in0=in0, scalar1=s1, scalar2=None, op0=op0), r, w)
        else:
            self.op(eng, lambda e: e.tensor_scalar(out=out, in0=in0, scalar1=s1, scalar2=s2, op0=op0, op1=op1), r, w)

    def stt(self, out, in0, scalar, in1, op0, op1, r, w):
        self.op("vector", lambda e: e.scalar_tensor_tensor(out=out, in0=in0, scalar=scalar, in1=in1, op0=op0, op1=op1), r, w)

    def act(self, out, in_, func, r, w, bias=None, scale=None):
        kw = {}
        if bias is not None:
            kw["bias"] = bias
        if scale is not None:
            kw["scale"] = scale
        self.op("scalar", lambda e: e.activation(out=out, in_=in_, func=func, **kw), r, w)

    def cp(self, eng, out, in_, r, w):
        if eng == "scalar":
            self.op(eng, lambda e: e.activation(out=out, in_=in_, func=AF.Copy), r, w)
        else:
            self.op(eng, lambda e: e.tensor_copy(out=out, in_=in_), r, w)

    def memset(self, eng, ap, val, w):
        self.op(eng, lambda e: e.memset(ap, val), (), w)


def fm(ap2d):
    return ap2d.rearrange("(c p) n -> p c n", p=128)


LAT_TILES = [(i * 512, 512, False) for i in range(8)]
ALL_TILES = LAT_TILES + [(T, 256, True)]


def stage_init(P, io, G):
    nc = P.nc
    G["identf"] = P.gsb([128, 128], F32, "identf")
    G["identb"] = P.gsb([128, 128], BF16, "identb")
    G["onesb"] = P.gsb([128, 128], BF16, "onesb")
    G["bones"] = P.gsb([128, 128], BF16, "bones")
    G["masks"] = P.gsb([128, 4, 128], BF16, "masks")
    G["perm"] = P.gsb([128, 128], BF16, "perm")
    G["rmask"] = P.gsb([128, 256], F32, "rmask")
    G["vec"] = P.gsb([128, NV, 8], F32, "vec")
    G["modv"] = P.gsb([128, 2, 6, 8, 2], F32, "modv")
    G["gg"] = P.gsb([128, 2, 2, 8, 2], F32, "gg")
    with P.phase("init"):
        P.dma("sync", G["identf"][:], io["c_ident"], writes=["identf"], sem="identf")
        P.dma("sync", G["rmask"][:], io["c_rmask"], writes=["rmask"], sem="rmask")
        P.dma("sync", G["vec"][:], io["vecs"], writes=["vec"], sem="vec")
        P.dma("gpsimd", G["identb"][:], io["c_ident"], writes=["identb"], sem="identb")
        P.dma("gpsimd", G["onesb"][:], io["c_ones"], writes=["onesb"], sem="onesb")
        P.dma("gpsimd", G["bones"][:], io["c_bones"], writes=["bones"], sem="bones")
        P.dma("gpsimd", G["masks"][:], io["c_masks"], writes=["masks"], sem="masks")
        P.dma("gpsimd", G["perm"][:], io["c_perm"], writes=["perm"], sem="perm")
        vec = G["vec"]
        P.ts("vector", vec[:, 17, :], vec[:, 16, :], -1.0, 1.0, ALU.mult, ALU.add, ["vec"], ["vec"])
        sv = P.sb([128, 8, 2], F32)
        svs = P.sb([128, 8, 2], F32)
        P.dma("sync", sv[:], io["cvec"], writes=["sv"], sem="sv")
        P.act(svs[:], sv[:], AF.Silu, ["sv"], ["svs"])
        brow = P.sb([2, 2 * 6144], F32)
        row = P.sb([2, 2 * 6144], F32)
        P.dma("sync", brow[:], io["b_mod"].rearrange("l n -> (l n)").partition_broadcast(2), writes=["brow"], sem="brow")
        wt = [P.sb([128, 8, 512], F32) for _ in range(2)]
        psr = [P.ps([128, 512], F32) for _ in range(2)]
        pst = P.ps([128, 512], F32)
        k = 0
        for l in range(2):
            for nb in range(12):
                b = k % 2
                k += 1
                P.dma("sync", wt[b][:], fm(io["w_mod"][l, :, nb * 512:(nb + 1) * 512]), writes=[f"wt{b}"], sem=f"wt{b}")
                for c in range(8):
                    P.mm(psr[b][0:2, :], svs[:, c, :], wt[b][:, c, :], c == 0, c == 7, ["svs", f"wt{b}"], [f"psr{b}"])
                o = l * 6144 + nb * 512
                P.tt("vector", row[:, o:o + 512], psr[b][0:2, :], brow[:, o:o + 512], ALU.add, [f"psr{b}", "brow"], ["row"])
        for l in range(2):
            for blk in range(48):
                o = l * 6144 + blk * 128
                P.tr(pst[:, l * 96 + blk * 2:l * 96 + blk * 2 + 2], row[0:2, o:o + 128], G["identf"][0:2, 0:2], ["row", "identf"], ["pst"])
        P.cp("vector", G["modv"][:].rearrange("p l m c j -> p (l m c j)"), pst[:, 0:192], ["pst"], ["modv"])
        modv, gg = G["modv"], G["gg"]
        for l in range(2):
            for kind in range(2):
                sc = modv[:, l, 1 + 3 * kind, :, :]
                nv = vec[:, (0 if kind == 0 else 2) + l, :].unsqueeze(2).broadcast_to([128, 8, 2])
                P.ts("vector", gg[:, l, kind, :, :], sc, 1.0, None, ALU.add, None, ["modv"], ["gg"])
                P.tt("vector", gg[:, l, kind, :, :], gg[:, l, kind, :, :], nv, ALU.mult, ["gg", "vec"], ["gg"])


def mod_scalars(G, l, kind, isctx):
    j = 1 if isctx else 0
    gains = [G["gg"][:, l, kind, c, j:j + 1] for c in range(8)]
    shifts = [G["modv"][:, l, 3 * kind, c, j:j + 1] for c in range(8)]
    gates = [G["modv"][:, l, 3 * kind + 2, c, j:j + 1] for c in range(8)]
    return gains, shifts, gates


def stage_norm(P, io, G, name, src, tiles, gains_fn, shifts_fn, dst_fn, out_dtype):
    with P.phase(name):
        xt = [P.sb([128, 8, 512], F32) for _ in range(2)]
        sq = P.sb([128, 8, 512], BF16)
        lnv = P.sb([128, 512], F32)
        rstd = P.sb([128, 512], F32)
        tmp = [P.sb([128, 512], F32) for _ in range(2)]
        ho = [P.sb([128, 8, 512], out_dtype) for _ in range(2)]
        ps = [P.ps([128, 512], F32) for _ in range(2)]

        def load(i):
            c0, tw, _ = tiles[i]
            b = i % 2
            P.dma("sync", xt[b][:, :, :tw], fm(src[:, c0:c0 + tw]), writes=[f"xt{b}"], sem=f"xt{b}")

        load(0)
        for i, (c0, tw, isctx) in enumerate(tiles):
            b = i % 2
            if i + 1 < len(tiles):
                load(i + 1)
            gains = gains_fn(isctx)
            shifts = shifts_fn(isctx)
            P.act(sq[:, :, :tw], xt[b][:, :, :tw], AF.Square, [f"xt{b}"], ["sq"])
            for c in range(8):
                P.mm(ps[b][:, :tw], G["onesb"][:], sq[:, c, :tw], c == 0, c == 7, ["sq", "onesb"], [f"ps{b}"])
            P.act(lnv[:, :tw], ps[b][:, :tw], AF.Ln, [f"ps{b}"], ["lnv"], bias=1e-6, scale=1.0 / D)
            P.act(rstd[:, :tw], lnv[:, :tw], AF.Exp, ["lnv"], ["rstd"], scale=-0.5)
            for c in range(8):
                if shifts is None:
                    P.stt(ho[b][:, c, :tw], xt[b][:, c, :tw], gains[c], rstd[:, :tw], ALU.mult, ALU.mult,
                          [f"xt{b}", "rstd", "vec", "gg"], [f"ho{b}"])
                else:
                    t = tmp[c % 2]
                    P.stt(t[:, :tw], xt[b][:, c, :tw], gains[c], rstd[:, :tw], ALU.mult, ALU.mult,
                          [f"xt{b}", "rstd", "vec", "gg"], [f"tmp{c % 2}"])
                    P.act(ho[b][:, c, :tw], t[:, :tw], AF.Identity, [f"tmp{c % 2}", "modv"], [f"ho{b}"], bias=shifts[c])
            P.dma("sync", dst_fn(c0, tw, isctx), ho[b][:, :, :tw], reads=[f"ho{b}"], writes=[("dst", i)], sem=f"ho{b}")


def stage_mlp(P, io, G, l, tiles, xa, hb):
    for half in range(2):
        with P.phase(f"mlp{l}{half}"):
            w1 = P.sb([128, 8, 2048], BF16)
            w2 = P.sb([128, 16, 1024], BF16)
            for q in range(2):
                P.dma("gpsimd", w1[:, :, q * 1024:(q + 1) * 1024],
                      fm(io["mlp_w1"][l, :, half * 2048 + q * 1024: half * 2048 + (q + 1) * 1024]), writes=["w1"], sem=f"w1{q}")
                P.dma("gpsimd", w2[:, q * 8:(q + 1) * 8, :],
                      io["mlp_w2"][l, half * 2048 + q * 1024: half * 2048 + (q + 1) * 1024, :].rearrange("(f p) n -> p f n", p=128),
                      writes=["w2"], sem=f"w2{q}")
            xt = [P.sb([128, 8, 512], F32) for _ in range(2)]
            ht = [P.sb([128, 8, 512], BF16) for _ in range(2)]
            h1 = P.sb([128, 16, 512], BF16)
            r1 = [P.sb([128, 512], F32) for _ in range(2)]
            ps = [P.ps([128, 512], F32) for _ in range(4)]

            def load(i):
                c0, tw, _ = tiles[i]
                b = i % 2
                P.dma("sync", ht[b][:, :, :tw], fm(hb[:, c0:c0 + tw]), writes=[f"ht{b}"], sem=f"ht{b}")
                P.dma("sync", xt[b][:, :, :tw], fm(xa[:, c0:c0 + tw]), reads=[("xa", i)], writes=[f"xt{b}"], sem=f"xt{b}")

            load(0)
            for i, (c0, tw, isctx) in enumerate(tiles):
                b = i % 2
                if i + 1 < len(tiles):
                    load(i + 1)
                _, _, gates = mod_scalars(G, l, 1, isctx)
                for fc in range(16):
                    pb = fc % 2
                    for c in range(8):
                        P.mm(ps[pb][:, :tw], w1[:, c, fc * 128:(fc + 1) * 128], ht[b][:, c, :tw], c == 0, c == 7,
                             ["w1", f"ht{b}"], [f"ps{pb}"])
                    P.act(r1[pb][:, :tw], ps[pb][:, :tw], AF.Relu, [f"ps{pb}"], [f"r1{pb}"])
                    P.tt("gpsimd", h1[:, fc, :tw], r1[pb][:, :tw], r1[pb][:, :tw], ALU.mult, [f"r1{pb}"], [("h1", fc)])
                for oc in range(8):
                    pb = 2 + oc % 2
                    for fc in range(16):
                        P.mm(ps[pb][:, :tw], w2[:, fc, oc * 128:(oc + 1) * 128], h1[:, fc, :tw], fc == 0, fc == 15,
                             ["w2", ("h1", fc)], [f"ps{pb}"])
                    P.stt(xt[b][:, oc, :tw], ps[pb][:, :tw], gates[oc], xt[b][:, oc, :tw], ALU.mult, ALU.add,
                          [f"ps{pb}", f"xt{b}", "modv"], [f"xt{b}"])
                P.dma("sync", fm(xa[:, c0:c0 + tw]), xt[b][:, :, :tw], reads=[f"xt{b}"], writes=[("xa", i)], sem=f"xt{b}")


RW_ORDER1 = [(True, 0)] + [(False, i) for i in range(16)]
RW_ORDER2 = [(True, 0)] + [(False, i) for i in range(15, -1, -1)]


def rw_scratch(io):
    S = {}
    S["yp"] = io.scratch("rw_yp", [17, 8, 128, 256], F32)
    S["sadd"] = io.scratch("rw_sadd", [17, 8, 128, 256], F32)
    S["vst"] = io.scratch("rw_vst", [17, 8, 128, 256], F32)
    S["gst"] = io.scratch("rw_gst", [17, 8, 128, 256], F32)
    S["gyb"] = io.scratch("rw_gyb", [17, 8, 128, 512], BF16)
    S["gsb"] = io.scratch("rw_gsb", [17, 8, 128, 512], BF16)
    S["gamb"] = io.scratch("rw_gamb", [17, 128, 32], F32)
    S["bon"] = io.scratch("rw_bon", [17, 128, 32], F32)
    return S


def stage_rwkv1(P, io, G, hp, S, dbg=None):
    vec, masks, identb, identf, bones, onesb, rmask = (G[k] for k in ("vec", "masks", "identb", "identf", "bones", "onesb", "rmask"))
    with P.phase("rwkv1"):
        wr = P.sb([128, 8, 1024], BF16)
        wk = P.sb([128, 8, 1024], BF16)
        wv = P.sb([128, 8, 1024], BF16)
        for w, nm in ((wr, "rwkv_wr"), (wk, "rwkv_wk"), (wv, "rwkv_wv")):
            P.dma("gpsimd", w[:], fm(io[nm]), writes=[nm], sem=nm)
        lw1 = P.sb([128, 8, 128], BF16)
        la1 = P.sb([128, 8, 128], BF16)
        g1 = P.sb([128, 8, 128], BF16)
        for d in range(2):
            P.dma("gpsimd", lw1[:, :, d * 64:(d + 1) * 64], io["rwkv_w1"][d].rearrange("(c p) j -> p c j", p=128), writes=["lw1"], sem=f"lw1{d}")
            P.dma("gpsimd", la1[:, :, d * 64:(d + 1) * 64], io["rwkv_a1"][d].rearrange("(c p) j -> p c j", p=128), writes=["la1"], sem=f"la1{d}")
        P.dma("gpsimd", g1[:], io["rwkv_g1"].rearrange("(c p) j -> p c j", p=128), writes=["g1"], sem="g1")
        w2s = P.sb([128, 1024], BF16)
        a2s = P.sb([128, 1024], BF16)
        g2 = P.sb([128, 1024], BF16)
        P.dma("gpsimd", w2s[:], io["rwkv_w2"].rearrange("d j f -> (d j) f"), writes=["w2s"], sem="w2s")
        P.dma("gpsimd", a2s[:], io["rwkv_a2"].rearrange("d j f -> (d j) f"), writes=["a2s"], sem="a2s")
        P.dma("gpsimd", g2[:], io["rwkv_g2"], writes=["g2"], sem="g2")

        hh = P.sb([128, 8, 384], F32)
        xx = P.sb([128, 8, 256], F32)
        xr = P.sb([128, 8, 256], BF16)
        xk = P.sb([128, 8, 256], BF16)
        xv = P.sb([128, 8, 256], BF16)
        xrot = P.sb([128, 8, 256], BF16)
        lwt = P.sb([128, 256], BF16)
        lat = P.sb([128, 256], BF16)
        sg = P.sb([128, 256], BF16)
        f32t = {}
        for nm in ("r", "k", "sw0", "sw1", "ag0", "ag1", "kq", "lnv", "rs", "kkn", "fac", "kd0", "kd1", "b0", "b1",
                   "L", "Lx", "Lb", "E1", "E2", "E3", "ks"):
            f32t[nm] = P.sb([128, 256], F32, "t_" + nm)
        sqb = P.sb([128, 256], BF16)
        RK = P.sb([128, 4, 2, 64], BF16)
        VTbd = P.sb([128, 4, 128], F32)
        GTbd = P.sb([128, 4, 128], F32)
        Vf = P.sb([128, 4, 64], F32)
        Gf = P.sb([128, 4, 64], F32)
        YPs = P.sb([128, 4, 64], F32)
        SAs = P.sb([128, 4, 64], F32)
        gamb_t = P.sb([128, 8, 4], F32)
        bon_t = P.sb([128, 8, 4], F32)
        Sf = P.sb([128, 8, 64], BF16)
        ARq = [[P.sb([128, 4, 2, 128], BF16, f"AR{q}{d}") for d in range(2)] for q in range(2)]
        KTq = [[P.sb([128, 4, 128], BF16, f"KT{q}{d}") for d in range(2)] for q in range(2)]
        BTq = [[P.sb([128, 4, 128], BF16, f"BT{q}{d}") for d in range(2)] for q in range(2)]
        Vbq = [P.sb([128, 4, 64], BF16, f"Vb{q}") for q in range(3)]
        gamq = [[P.sb([128, 4], F32, f"gam{q}{d}") for d in range(2)] for q in range(3)]
        inv = []
        for d in range(2):
            st = {}
            for nm, shp in (("Atok", [128, 4, 128]), ("Btok", [128, 4, 128]), ("MQ", [128, 4, 256]), ("MWa", [128, 4, 2, 128]),
                            ("MWb", [128, 4, 2, 128]), ("MTa", [128, 4, 128]), ("MTb", [128, 4, 128])):
                st[nm] = P.sb(shp, BF16, f"i{d}_{nm}")
            inv.append(st)
        fin = []
        for q in range(2):
            row = []
            for d in range(2):
                st = {}
                for nm, shp in (("Ktok", [128, 4, 128]), ("NP", [128, 4, 256]), ("XW", [128, 4, 256]), ("NVb", [128, 4, 64]),
                                ("GY", [128, 4, 128]), ("GS", [128, 4, 128])):
                    st[nm] = P.sb(shp, BF16, f"f{q}{d}_{nm}")
                row.append(st)
            fin.append(row)
        ppt = [P.ps([128, 512], F32) for _ in range(2)]
        pp = [t_[:, 0:256] for t_ in ppt]
        pf = P.ps([128, 512], F32)
        pb = [P.ps([128, 512], F32) for _ in range(5)]
        cnt = {"pp": 0, "pb": 0}
        nmod = {"pp": 2, "pb": 5}

        def nxt(kind):
            i = cnt[kind] % nmod[kind]
            cnt[kind] += 1
            return i

        for q in range(2):
            for d in range(2):
                P.memset("gpsimd", ARq[q][d][:], 0.0, [f"AR{q}{d}"])
                P.memset("gpsimd", KTq[q][d][:], 0.0, [f"KT{q}{d}"])
                P.memset("gpsimd", BTq[q][d][:], 0.0, [f"BT{q}{d}"])
        P.memset("gpsimd", RK[:], 0.0, ["RK"])
        P.memset("gpsimd", VTbd[:], 0.0, ["VTbd"])
        P.memset("gpsimd", GTbd[:], 0.0, ["GTbd"])
        P.memset("gpsimd", Sf[:], 0.0, [("Sf", p) for p in range(8)])

        def v3(ap):
            return ap.rearrange("p (u s) -> p u s", s=64)

        def u128(ap):
            return ap.rearrange("p (u x) -> p u x", x=128)

        def load_hh(ti):
            isctx, idx = RW_ORDER1[ti]
            off = 4288 if isctx else 64 + 256 * idx
            P.dma("sync", hh[:], fm(hp[:, off - 64: off + 320]), writes=["hh"], sem="hh")

        def proj8(w_cols_fn, xb, bn, extra_r):
            i = nxt("pp")
            for c in range(8):
                P.mm(pp[i], w_cols_fn(c), xb[:, c, :], c == 0, c == 7, [(bn, c)] + extra_r, [f"pp{i}"])
            return i

        def tprep(ti):
            isctx, idx = RW_ORDER1[ti]
            hc = hh[:, :, 64:320]
            XXW = [("xx", c) for c in range(8)]
            if not isctx:
                h4 = hh[:, :, 64:320].rearrange("p c (r w) -> p c r w", w=64)
                x4 = xx[:].rearrange("p c (r w) -> p c r w", w=64)
                P.tt("vector", x4[:, 0:2, :, 1:64], h4[:, 0:2, :, 0:63], h4[:, 0:2, :, 1:64], ALU.subtract, ["hh"], XXW[0:2])
                P.ts("gpsimd", x4[:, 0:2, :, 0:1], h4[:, 0:2, :, 0:1], -1.0, 0.0, ALU.mult, ALU.add, ["hh"], [("xxe", 0)])
                P.tt("vector", x4[:, 2:4, :, 0:63], h4[:, 2:4, :, 1:64], h4[:, 2:4, :, 0:63], ALU.subtract, ["hh"], XXW[2:4])
                P.ts("gpsimd", x4[:, 2:4, :, 63:64], h4[:, 2:4, :, 63:64], -1.0, 0.0, ALU.mult, ALU.add, ["hh"], [("xxe", 1)])
                P.tt("gpsimd", xx[:, 4:6, :], hh[:, 4:6, 0:256], hh[:, 4:6, 64:320], ALU.subtract, ["hh"], XXW[4:6])
                P.tt("gpsimd", xx[:, 6:8, :], hh[:, 6:8, 128:384], hh[:, 6:8, 64:320], ALU.subtract, ["hh"], XXW[6:8])
            else:
                P.tt("vector", xx[:, 0:4, :], hh[:, 0:4, 63:319], hh[:, 0:4, 64:320], ALU.subtract, ["hh"], XXW[0:4] + [("xxe", 0)])
                P.tt("gpsimd", xx[:, 4:8, :], hh[:, 4:8, 65:321], hh[:, 4:8, 64:320], ALU.subtract, ["hh"], XXW[4:8] + [("xxe", 1)])
            yield

            def mk_xj(j, buf, bn):
                for c in range(8):
                    P.stt(buf[:, c, :], xx[:, c, :], vec[:, 5 + j, c:c + 1], hc[:, c, :], ALU.mult, ALU.add,
                          [("xx", c), ("xxe", 0), ("xxe", 1), "hh", "vec"], [(bn, c)])

            mk_xj(1, xrot, "xrot")
            yield
            i = proj8(lambda c: lw1[:, c, :], xrot, "xrot", ["lw1"])
            P.act(lwt[:], pp[i], AF.Tanh, [f"pp{i}"], ["lwt"])
            yield
            mk_xj(4, xrot, "xrot")
            yield
            i = proj8(lambda c: la1[:, c, :], xrot, "xrot", ["la1"])
            P.cp("scalar", lat[:], pp[i], [f"pp{i}"], ["lat"])
            yield
            mk_xj(5, xrot, "xrot")
            yield
            i = proj8(lambda c: g1[:, c, :], xrot, "xrot", ["g1"])
            P.act(sg[:], pp[i], AF.Sigmoid, [f"pp{i}"], ["sg"])
            yield
            mk_xj(0, xr, "xr")
            yield
            mk_xj(2, xk, "xk")
            yield
            mk_xj(3, xv, "xv")
            if ti + 1 < len(RW_ORDER1):
                load_hh(ti + 1)
            yield

        def prep(ti, oc, q, z):
            isctx, idx = RW_ORDER1[ti]
            tg = 16 if isctx else idx
            cs = slice(oc * 128, (oc + 1) * 128)
            t = f32t
            AR, KT, BT, Vb, gam = ARq[q], KTq[q], BTq[q], Vbq[z], gamq[z]
            i = proj8(lambda c: wr[:, c, cs], xr, "xr", ["rwkv_wr"])
            P.cp("scalar", t["r"][:], pp[i], [f"pp{i}"], ["r"])
            i = proj8(lambda c: wk[:, c, cs], xk, "xk", ["rwkv_wk"])
            P.cp("scalar", t["k"][:], pp[i], [f"pp{i}"], ["k"])
            i = proj8(lambda c: wv[:, c, cs], xv, "xv", ["rwkv_wv"])
            vt4 = VTbd[:].rearrange("p u (h s) -> p u h s", h=2)
            for h2 in range(2):
                sl = slice(h2 * 64, (h2 + 1) * 64)
                P.cp("scalar", vt4[sl, :, h2, :], v3(pp[i][sl, :]), [f"pp{i}"], ["VTbd"])
            i = nxt("pp")
            P.mm(pp[i], g2[:, cs], sg[:], True, True, ["g2", "sg"], [f"pp{i}"])
            gt4 = GTbd[:].rearrange("p u (h s) -> p u h s", h=2)
            for h2 in range(2):
                sl = slice(h2 * 64, (h2 + 1) * 64)
                P.cp("scalar", gt4[sl, :, h2, :], v3(pp[i][sl, :]), [f"pp{i}"], ["GTbd"])
            yield
            j = nxt("pb")
            for u in range(4):
                P.tr(pb[j][:, u * 128:(u + 1) * 128], VTbd[:, u, :], identf[:], ["VTbd", "identf"], [f"pb{j}"])
            pv = u128(pb[j][:])
            for h2 in range(2):
                sl = slice(h2 * 64, (h2 + 1) * 64)
                P.cp("scalar", Vf[sl, :, :], pv[sl, :, h2 * 64:(h2 + 1) * 64], [f"pb{j}"], ["Vf"])
            P.cp("gpsimd", Vb[:], Vf[:], ["Vf"], [f"Vb{z}"])
            P.dma("sync", S["vst"][tg, oc].rearrange("p (u s) -> p u s", s=64), Vf[:], reads=["Vf"], writes=[("vst", tg, oc)], sem="Vf")
            j = nxt("pb")
            for u in range(4):
                P.tr(pb[j][:, u * 128:(u + 1) * 128], GTbd[:, u, :], identf[:], ["GTbd", "identf"], [f"pb{j}"])
            pv = u128(pb[j][:])
            for h2 in range(2):
                sl = slice(h2 * 64, (h2 + 1) * 64)
                P.cp("scalar", Gf[sl, :, :], pv[sl, :, h2 * 64:(h2 + 1) * 64], [f"pb{j}"], ["Gf"])
            P.dma("sync", S["gst"][tg, oc].rearrange("p (u s) -> p u s", s=64), Gf[:], reads=["Gf"], writes=[("gst", tg, oc)], sem="Gf")
            yield
            for d in range(2):
                dl = slice(d * 64, (d + 1) * 64)
                i = nxt("pp")
                P.mm(pp[i], w2s[dl, cs], lwt[dl, :], True, True, ["w2s", "lwt"], [f"pp{i}"])
                P.act(t[f"sw{d}"][:], pp[i], AF.Sigmoid, [f"pp{i}", "vec"], [f"sw{d}"], bias=vec[:, 11 + d, oc:oc + 1])
                i = nxt("pp")
                P.mm(pp[i], a2s[dl, cs], lat[dl, :], True, True, ["a2s", "lat"], [f"pp{i}"])
                P.act(t[f"ag{d}"][:], pp[i], AF.Sigmoid, [f"pp{i}", "vec"], [f"ag{d}"], bias=vec[:, 13 + d, oc:oc + 1])
            yield
            P.ts("vector", t["kq"][:], t["k"][:], vec[:, 15, oc:oc + 1], None, ALU.mult, None, ["k", "vec"], ["kq"])
            P.act(sqb[:], t["kq"][:], AF.Square, ["kq"], ["sqb"])
            i = nxt("pp")
            P.mm(pp[i], bones[:], sqb[:], True, True, ["bones", "sqb"], [f"pp{i}"])
            P.act(t["lnv"][:], pp[i], AF.Ln, [f"pp{i}"], ["lnv"], bias=1e-12)
            P.act(t["rs"][:], t["lnv"][:], AF.Exp, ["lnv"], ["rs"], scale=-0.5)
            P.tt("gpsimd", t["kkn"][:], t["kq"][:], t["rs"][:], ALU.mult, ["kq", "rs"], ["kkn"])
            for d in range(2):
                sw, ag, kd, bb = t[f"sw{d}"], t[f"ag{d}"], t[f"kd{d}"], t[f"b{d}"]
                P.ts("gpsimd", t["fac"][:], ag[:], vec[:, 16, oc:oc + 1], vec[:, 17, oc:oc + 1], ALU.mult, ALU.add, [f"ag{d}", "vec"], ["fac"])
                P.tt("gpsimd", kd[:], t["k"][:], t["fac"][:], ALU.mult, ["k", "fac"], [f"kd{d}"])
                P.tt("gpsimd", bb[:], t["kkn"][:], ag[:], ALU.mult, ["kkn", f"ag{d}"], [f"b{d}"])
                P.op("vector", lambda e, sw=sw: e.tensor_tensor_scan(out=t["L"][:], data0=rmask[:], data1=sw[:], initial=0.0,
                                                                      op0=ALU.mult, op1=ALU.add), [f"sw{d}", "rmask"], ["L"])
                L3 = v3(t["L"][:])
                if d == 0:
                    P.tt("gpsimd", t["Lx"][:], t["L"][:], sw[:], ALU.subtract, ["L", f"sw{d}"], ["Lx"])
                    Li, Lin = t["L"], "L"
                else:
                    P.tt("gpsimd", v3(t["Lx"][:]), L3[:, :, 63:64].broadcast_to([128, 4, 64]), L3, ALU.subtract, ["L"], ["Lx"])
                    P.tt("gpsimd", t["Lb"][:], t["Lx"][:], sw[:], ALU.add, ["Lx", f"sw{d}"], ["Lb"])
                    Li, Lin = t["Lb"], "Lb"
                P.act(t["E1"][:], Li[:], AF.Exp, [Lin], ["E1"], scale=-C0)
                P.act(t["E3"][:], Li[:], AF.Exp, [Lin], ["E3"], scale=C0)
                P.act(t["E2"][:], t["Lx"][:], AF.Exp, ["Lx"], ["E2"], scale=-C0)
                ar5 = AR[d][:].rearrange("p u a (h s) -> p u a h s", h=2)
                kt4 = KT[d][:].rearrange("p u (h s) -> p u h s", h=2)
                bt4 = BT[d][:].rearrange("p u (h s) -> p u h s", h=2)
                for h2 in range(2):
                    sl = slice(h2 * 64, (h2 + 1) * 64)
                    P.stt(ar5[sl, :, 0, h2, :], v3(t["kkn"][sl, :]), -1.0, v3(t["E2"][sl, :]), ALU.mult, ALU.mult, ["kkn", "E2"], [f"AR{q}{d}"])
                    P.tt("gpsimd", ar5[sl, :, 1, h2, :], v3(t["r"][sl, :]), v3(t["E1"][sl, :]), ALU.mult, ["r", "E1"], [f"AR{q}{d}"])
                    P.tt("vector", kt4[sl, :, h2, :], v3(kd[sl, :]), v3(t["E3"][sl, :]), ALU.mult, [f"kd{d}", "E3"], [f"KT{q}{d}"])
                    P.tt("gpsimd", bt4[sl, :, h2, :], v3(bb[sl, :]), v3(t["E3"][sl, :]), ALU.mult, [f"b{d}", "E3"], [f"BT{q}{d}"])
                E13 = v3(t["E1"][:])
                gsrc = E13[:, :, 63] if d == 0 else E13[:, :, 0]
                P.cp("vector", gam[d][:], gsrc, ["E1"], [f"gam{z}{d}"])
                if d == 1:
                    P.cp("gpsimd", gamb_t[:, oc, :], gam[1][:], [f"gam{z}1"], ["gamb_t"])
                yield
            P.tt("gpsimd", t["ks"][:], t["kd0"][:], t["kd1"][:], ALU.add, ["kd0", "kd1"], ["ks"])
            for h2 in range(2):
                sl = slice(h2 * 64, (h2 + 1) * 64)
                P.stt(RK[sl, :, h2, :], v3(t["r"][sl, :]), vec[sl, 18, oc:oc + 1], v3(t["ks"][sl, :]), ALU.mult, ALU.mult, ["r", "ks", "vec"], ["RK"])
            i = nxt("pp")
            for u in range(4):
                P.mm(pp[i][:, u:u + 1], RK[:, u, :, :].rearrange("p h s -> p (h s)"), onesb[:, 0:1], True, True, ["RK", "onesb"], [f"pp{i}"])
            P.cp("scalar", bon_t[:, oc, :], pp[i][:, 0:4], [f"pp{i}"], ["bon_t"])
            if oc == 7:
                P.dma("sync", S["gamb"][tg], gamb_t[:].rearrange("p a b -> p (a b)"), reads=["gamb_t"], writes=[("gamb", tg)], sem="gamb_t")
                P.dma("sync", S["bon"][tg], bon_t[:].rearrange("p a b -> p (a b)"), reads=["bon_t"], writes=[("bon", tg)], sem="bon_t")
            yield

        def chain(ti, oc, q, d, z):
            AR, KT, BT, Vb = ARq[q][d], KTq[q][d], BTq[q][d], Vbq[z]
            ARn, KTn, BTn, Vbn = f"AR{q}{d}", f"KT{q}{d}", f"BT{q}{d}", f"Vb{z}"
            iv, fn = inv[d], fin[q][d]
            IR = lambda nm: f"i{d}_{nm}"
            FR = lambda nm: f"f{q}{d}_{nm}"
            mS, mC = (0, 2) if d == 0 else (2, 0)
            mSI = masks[:, mS:mS + 2, :].rearrange("p a b -> p (a b)").unsqueeze(1).broadcast_to([128, 4, 256])
            mCb = masks[:, mC, :].unsqueeze(1).broadcast_to([128, 4, 128])
            idb = identb[:].unsqueeze(1).broadcast_to([128, 4, 128])
            for src, srcn, dst, dstn in ((AR[:, :, 0, :], ARn, iv["Atok"], IR("Atok")), (BT[:], BTn, iv["Btok"], IR("Btok")),
                                         (KT[:], KTn, fn["Ktok"], FR("Ktok"))):
                j = nxt("pb")
                pbt = pb[j][:].bitcast(BF16)
                for u in range(4):
                    P.tr(pbt[:, u * 128:(u + 1) * 128], src[:, u, :], identb[:], [srcn, "identb"], [f"pb{j}"])
                P.cp("scalar", dst[:].rearrange("p u x -> p (u x)"), pbt[:, 0:512], [f"pb{j}"], [dstn])
            mSb = masks[:, mS, :].unsqueeze(1).broadcast_to([128, 4, 128])
            mIb = masks[:, mS + 1, :].unsqueeze(1).broadcast_to([128, 4, 128])

            def two_bank(mm_fn):
                j0, j1 = nxt("pb"), nxt("pb")
                for u in range(4):
                    mm_fn(u, pb[j0][:, u * 128:(u + 1) * 128], f"pb{j0}", pb[j1][:, u * 128:(u + 1) * 128], f"pb{j1}")
                return j0, j1

            for lhs, lhsn, dst, dstn in ((BT, BTn, iv["MQ"], IR("MQ")), (KT, KTn, fn["NP"], FR("NP"))):
                def mm_ab(u, o0, n0, o1, n1, lhs=lhs, lhsn=lhsn):
                    P.mm(o0, lhs[:, u, :], AR[:, u, 0, :], True, True, [lhsn, ARn], [n0])
                    P.mm(o1, lhs[:, u, :], AR[:, u, 1, :], True, True, [lhsn, ARn], [n1])
                j0, j1 = two_bank(mm_ab)
                P.tt("vector", dst[:, :, 0:128], u128(pb[j0][:]), mSb, ALU.mult, [f"pb{j0}", "masks"], [dstn])
                P.tt("vector", dst[:, :, 128:256], u128(pb[j1][:]), mIb, ALU.mult, [f"pb{j1}", "masks"], [dstn])
            j = nxt("pb")
            for u in range(4):
                P.mm(pb[j][:, u * 128:(u + 1) * 128], AR[:, u, 0, :], BT[:, u, :], True, True, [ARn, BTn], [f"pb{j}"])
            cur, curn, nx, nxn = iv["MWa"], IR("MWa"), iv["MWb"], IR("MWb")
            P.tt("vector", cur[:, :, 0, :], u128(pb[j][:]), mCb, ALU.mult, [f"pb{j}", "masks"], [curn])
            yield
            j = nxt("pb")
            for u in range(4):
                P.mm(pb[j][:, u * 128:(u + 1) * 128], iv["MQ"][:, u, 0:128], cur[:, u, 0, :], True, True, [IR("MQ"), curn], [f"pb{j}"])
            P.cp("scalar", nx[:, :, 0, :], u128(pb[j][:]), [f"pb{j}"], [nxn])
            P.tt("gpsimd", nx[:, :, 1, :], cur[:, :, 0, :], idb, ALU.add, [curn, "identb"], [nxn])
            j = nxt("pb")
            for u in range(4):
                P.mm(pb[j][:, u * 128:(u + 1) * 128], cur[:, u, 0, :], iv["MQ"][:, u, 0:128], True, True, [IR("MQ"), curn], [f"pb{j}"])
            curT, curTn, nxT, nxTn = iv["MTa"], IR("MTa"), iv["MTb"], IR("MTb")
            P.cp("scalar", curT[:], u128(pb[j][:]), [f"pb{j}"], [curTn])
            cur, curn, nx, nxn = nx, nxn, cur, curn
            yield
            for lev in range(1, 5):
                def mm_lev(u, o0, n0, o1, n1, cur=cur, curn=curn, curT=curT, curTn=curTn):
                    P.mm(o0, curT[:, u, :], cur[:, u, 0, :], True, True, [curTn, curn], [n0])
                    P.mm(o1, curT[:, u, :], cur[:, u, 1, :], True, True, [curTn, curn], [n1])
                j0, j1 = two_bank(mm_lev)
                P.cp("scalar", nx[:, :, 0, :], u128(pb[j0][:]), [f"pb{j0}"], [nxn])
                P.tt("vector", nx[:, :, 1, :], u128(pb[j1][:]), cur[:, :, 1, :], ALU.add, [f"pb{j1}", curn], [nxn])
                j = nxt("pb")
                for u in range(4):
                    P.mm(pb[j][:, u * 128:(u + 1) * 128], cur[:, u, 0, :], curT[:, u, :], True, True, [curn, curTn], [f"pb{j}"])
                P.cp("scalar", nxT[:], u128(pb[j][:]), [f"pb{j}"], [nxTn])
                cur, curn, nx, nxn = nx, nxn, cur, curn
                curT, curTn, nxT, nxTn = nxT, nxTn, curT, curTn
                yield
            j = nxt("pb")
            for u in range(4):
                P.mm(pb[j][:, u * 128:(u + 1) * 128], curT[:, u, :], cur[:, u, 1, :], True, True, [curTn, curn], [f"pb{j}"])
            P.tt("vector", nx[:, :, 1, :], u128(pb[j][:]), cur[:, :, 1, :], ALU.add, [f"pb{j}", curn], [nxn])
            W6, W6n = nx, nxn
            j = nxt("pb")
            for u in range(4):
                P.mm(pb[j][:, u * 64:(u + 1) * 64], fn["NP"][:, u, 0:128], Vb[:, u, :], True, True, [FR("NP"), Vbn], [f"pb{j}"])
            P.cp("scalar", fn["NVb"][:].rearrange("p u x -> p (u x)"), pb[j][:, 0:256], [f"pb{j}"], [FR("NVb")])
            yield

            def mm_d(u, o0, n0, o1, n1):
                P.mm(o0, W6[:, u, 1, :], iv["MQ"][:, u, 128:256], True, True, [W6n, IR("MQ")], [n0])
                P.mm(o1, W6[:, u, 1, :], iv["Btok"][:, u, :], True, True, [W6n, IR("Btok")], [n1])
            j0, j1 = two_bank(mm_d)
            P.cp("scalar", fn["XW"][:, :, 0:128], u128(pb[j0][:]), [f"pb{j0}"], [FR("XW")])
            P.cp("vector", fn["XW"][:, :, 128:256], u128(pb[j1][:]), [f"pb{j1}"], [FR("XW")])
            yield

            def mm_f(u, o0, n0, o1, n1):
                P.mm(o0, iv["Atok"][:, u, :], fn["XW"][:, u, 0:128], True, True, [IR("Atok"), FR("XW")], [n0])
                P.mm(o1, iv["Atok"][:, u, :], fn["XW"][:, u, 128:256], True, True, [IR("Atok"), FR("XW")], [n1])
            j0, j1 = two_bank(mm_f)
            P.tt("vector", fn["GY"][:], u128(pb[j0][:]), AR[:, :, 1, :], ALU.add, [f"pb{j0}", ARn], [FR("GY")])
            P.tt("vector", fn["GS"][:], u128(pb[j1][:]), idb, ALU.add, [f"pb{j1}", "identb"], [FR("GS")])
            yield

        def finish(ti, oc, q, z):
            isctx, idx = RW_ORDER1[ti]
            tg = 16 if isctx else idx
            sf, sb_ = fin[q]
            F0 = lambda nm: f"f{q}0_{nm}"
            F1 = lambda nm: f"f{q}1_{nm}"
            Vb, Vbn, gam = Vbq[z], f"Vb{z}", gamq[z]
            SFR = ("Sf", oc)
            for u in range(4):
                yo = pf[:, u * 64:(u + 1) * 64]
                P.mm(yo, sf["NP"][:, u, 128:256], Vb[:, u, :], True, False, [F0("NP"), Vbn], ["pf"])
                P.mm(yo, sf["XW"][:, u, 0:128], sf["NVb"][:, u, :], False, False, [F0("XW"), F0("NVb")], ["pf"])
                P.mm(yo, sb_["NP"][:, u, 128:256], Vb[:, u, :], False, False, [F1("NP"), Vbn], ["pf"])
                P.mm(yo, sb_["XW"][:, u, 0:128], sb_["NVb"][:, u, :], False, False, [F1("XW"), F1("NVb")], ["pf"])
                P.mm(yo, sf["GY"][:, u, :], Sf[:, oc, :], False, True, [F0("GY"), SFR], ["pf"])
                so = pf[:, 256:320]
                P.mm(so, sf["Ktok"][:, u, :], Vb[:, u, :], True, False, [F0("Ktok"), Vbn], ["pf"])
                P.mm(so, sf["XW"][:, u, 128:256], sf["NVb"][:, u, :], False, False, [F0("XW"), F0("NVb")], ["pf"])
                P.mm(so, sf["GS"][:, u, :], Sf[:, oc, :], False, True, [F0("GS"), SFR], ["pf"])
                P.ts("vector", Sf[:, oc, :], so, gam[0][:, u:u + 1], None, ALU.mult, None, ["pf", f"gam{z}0"], [SFR])
                yield
            P.cp("vector", YPs[:].rearrange("p u x -> p (u x)"), pf[:, 0:256], ["pf"], ["YPs"])
            P.dma("sync", S["yp"][tg, oc], YPs[:].rearrange("p u x -> p (u x)"), reads=["YPs"], writes=[("yp", tg, oc)], sem="YPs")
            j = nxt("pb")
            for u in range(4):
                so = pb[j][:, u * 64:(u + 1) * 64]
                P.mm(so, sb_["Ktok"][:, u, :], Vb[:, u, :], True, False, [F1("Ktok"), Vbn], [f"pb{j}"])
                P.mm(so, sb_["XW"][:, u, 128:256], sb_["NVb"][:, u, :], False, True, [F1("XW"), F1("NVb")], [f"pb{j}"])
            P.cp("scalar", SAs[:].rearrange("p u x -> p (u x)"), pb[j][:, 0:256], [f"pb{j}"], ["SAs"])
            P.dma("sync", S["sadd"][tg, oc], SAs[:].rearrange("p u x -> p (u x)"), reads=["SAs"], writes=[("sadd", tg, oc)], sem="SAs")
            P.dma("sync", S["gyb"][tg, oc], sb_["GY"][:].rearrange("p u x -> p (u x)"), reads=[F1("GY")], writes=[("gyb", tg, oc)], sem=F1("GY"))
            P.dma("sync", S["gsb"][tg, oc], sb_["GS"][:].rearrange("p u x -> p (u x)"), reads=[F1("GS")], writes=[("gsb", tg, oc)], sem=F1("GS"))
            yield

        NT = len(RW_ORDER1)
        NJ = NT * 8
        done = {"prep": set(), "c0": set(), "c1": set(), "fin": set(), "tprep": set()}

        def stream_P():
            for ti in range(NT):
                yield ("tprep", ti, lambda ti=ti: (ti == 0 or ("prep", (ti - 1) * 8 + 7) in donef), lambda ti=ti: tprep(ti))
                for oc in range(8):
                    k = ti * 8 + oc
                    yield ("prep", k, lambda k=k: ((k < 2 or (("c0", k - 2) in donef and ("c1", k - 2) in donef)) and (k < 3 or ("fin", k - 3) in donef)),
                           lambda ti=ti, oc=oc, k=k: prep(ti, oc, k % 2, k % 3))

        def stream_C(d):
            for k in range(NJ):
                ti, oc = divmod(k, 8)
                yield (f"c{d}", k, lambda k=k: (("prep", k) in donef and (k < 2 or ("fin", k - 2) in donef)),
                       lambda ti=ti, oc=oc, k=k: chain(ti, oc, k % 2, d, k % 3))

        def stream_F():
            for k in range(NJ):
                ti, oc = divmod(k, 8)
                yield ("fin", k, lambda k=k: (("c0", k) in donef and ("c1", k) in donef),
                       lambda ti=ti, oc=oc, k=k: finish(ti, oc, k % 2, k % 3))

        donef = set()
        load_hh(0)
        streams = [stream_C(0), stream_C(1), stream_F(), stream_P()]
        cur = [None] * 4
        pend = [None] * 4
        alive = [True] * 4
        while any(alive):
            progressed = False
            for si in range(4):
                if not alive[si]:
                    continue
                if cur[si] is None:
                    if pend[si] is None:
                        try:
                            pend[si] = next(streams[si])
                        except StopIteration:
                            alive[si] = False
                            continue
                    kind, k, ready, mk = pend[si]
                    if not ready():
                        continue
                    cur[si] = (kind, k, mk())
                    pend[si] = None
                kind, k, gen = cur[si]
                try:
                    next(gen)
                    progressed = True
                except StopIteration:
                    donef.add((kind, k))
                    cur[si] = None
                    progressed = True
            assert progressed or not any(alive), "scheduler stuck"


def stage_rwkv2(P, io, G, S, src, xa):
    vec, identb = G["vec"], G["identb"]
    GN_EPS = 64e-5
    with P.phase("rwkv2"):
        wo = P.sb([64, 16, 1024], BF16)
        P.dma("gpsimd", wo[:], io["rwkv_wo"].rearrange("(h v) f -> v h f", v=64), writes=["wo"], sem="wo")
        lnw = P.sb([128, 8, 64], F32)
        lnb = P.sb([128, 8, 64], F32)
        P.dma("sync", lnw[:], io["lnw_st"], writes=["lnw"], sem="lnw")
        P.dma("sync", lnb[:], io["lnb_st"], writes=["lnb"], sem="lnb")
        big = {}
        for nm in ("yp", "sadd", "vst", "gst"):
            big[nm] = [P.sb([128, 8, 256], F32, f"l_{nm}{b}") for b in range(2)]
        for nm in ("gyb", "gsb"):
            big[nm] = [P.sb([128, 8, 512], BF16, f"l_{nm}{b}") for b in range(2)]
        gamb = [P.sb([128, 8, 4], F32) for _ in range(2)]
        bon = [P.sb([128, 8, 4], F32) for _ in range(2)]
        xt = [P.sb([128, 8, 256], F32) for _ in range(2)]
        Sb = P.sb([128, 8, 64], BF16)
        ysb2 = [P.sb([128, 8, 64], F32) for _ in range(2)]
        ysq2 = [P.sb([128, 8, 64], F32) for _ in range(2)]
        tmpS = P.sb([128, 8, 64], F32)
        yn2 = [P.sb([128, 8, 64], F32) for _ in range(2)]
        bv2 = [P.sb([128, 8, 64], F32) for _ in range(2)]
        ob2 = [P.sb([128, 8, 64], BF16) for _ in range(2)]
        st2 = [{nm: P.sb([128, 8], F32, f"g{k_}_" + nm) for nm in ("s1", "s2", "mean", "msq", "var", "lnv", "rstd")} for k_ in range(2)]
        OT = P.sb([64, 16, 256], BF16)
        py = [P.ps([128, 512], F32) for _ in range(2)]
        pS = P.ps([128, 512], F32)
        ptr = P.ps([128, 1024], F32)
        pw = [P.ps([128, 512], F32) for _ in range(2)]
        P.memset("gpsimd", Sb[:], 0.0, ["Sb"])

        def load(k):
            isctx, idx = RW_ORDER2[k]
            tg = 16 if isctx else idx
            b = k % 2
            for nm in ("yp", "sadd", "vst", "gst", "gyb", "gsb"):
                P.dma("sync", big[nm][b][:], S[nm][tg].rearrange("o p x -> p o x"), writes=[f"{nm}{b}"], sem=f"{nm}{b}")
            P.dma("sync", gamb[b][:].rearrange("p a b -> p (a b)"), S["gamb"][tg], writes=[f"gamb{b}"], sem=f"gamb{b}")
            P.dma("sync", bon[b][:].rearrange("p a b -> p (a b)"), S["bon"][tg], writes=[f"bon{b}"], sem=f"bon{b}")
            c0 = T if isctx else idx * 256
            P.dma("sync", xt[b][:], fm(src[:, c0:c0 + 256]), writes=[f"xt{b}"], sem=f"xt{b}")

        load(0)
        for k, (isctx, idx) in enumerate(RW_ORDER2):
            b = k % 2
            if k + 1 < len(RW_ORDER2):
                load(k + 1)
            c0 = T if isctx else idx * 256
            _, _, gates = mod_scalars(G, 0, 0, isctx)
            bc = lambda ap: ap.unsqueeze(2).broadcast_to([128, 8, 64])
            def chain_part(u):
                us = slice(u * 64, (u + 1) * 64)
                q_ = u % 2
                for oc in range(8):
                    P.mm(py[q_][:, oc * 64:(oc + 1) * 64], big["gyb"][b][:, oc, u * 128:(u + 1) * 128], Sb[:, oc, :], True, True, [f"gyb{b}", "Sb"], [f"py{q_}"])
                for oc in range(8):
                    P.mm(pS[:, oc * 64:(oc + 1) * 64], big["gsb"][b][:, oc, u * 128:(u + 1) * 128], Sb[:, oc, :], True, True, [f"gsb{b}", "Sb"], ["pS"])
                pS3 = pS[:].rearrange("p (o v) -> p o v", v=64)
                P.tt("vector", tmpS[:], pS3, big["sadd"][b][:, :, us], ALU.add, ["pS", f"sadd{b}"], ["tmpS"])
                P.tt("vector", Sb[:], tmpS[:], bc(gamb[b][:, :, u]), ALU.mult, ["tmpS", f"gamb{b}"], ["Sb"])

            def read_part(u):
                us = slice(u * 64, (u + 1) * 64)
                q_ = u % 2
                ysb, ysq, yn, bv, ob, st = ysb2[q_], ysq2[q_], yn2[q_], bv2[q_], ob2[q_], st2[q_]
                N = lambda nm: f"{nm}{q_}"
                py3 = py[q_][:].rearrange("p (o v) -> p o v", v=64)
                P.tt("vector", ysb[:], py3, big["yp"][b][:, :, us], ALU.add, [f"py{q_}", f"yp{b}"], [N("ysb")])
                P.tt("gpsimd", bv[:], big["vst"][b][:, :, us], bc(bon[b][:, :, u]), ALU.mult, [f"vst{b}", f"bon{b}"], [N("bv")])
                yield
                P.op("vector", lambda e: e.tensor_reduce(out=st["s1"][:], in_=ysb[:], axis=AX.X, op=ALU.add), [N("ysb")], [N("s1")])
                P.tt("gpsimd", ysq[:], ysb[:], ysb[:], ALU.mult, [N("ysb")], [N("ysq")])
                yield
                P.op("vector", lambda e: e.tensor_reduce(out=st["s2"][:], in_=ysq[:], axis=AX.X, op=ALU.add), [N("ysq")], [N("s2")])
                P.ts("vector", st["mean"][:], st["s1"][:], 1.0 / 64, None, ALU.mult, None, [N("s1")], [N("mean")])
                P.tt("vector", st["msq"][:], st["mean"][:], st["mean"][:], ALU.mult, [N("mean")], [N("msq")])
                P.stt(st["var"][:], st["s2"][:], 1.0 / 64, st["msq"][:], ALU.mult, ALU.subtract, [N("s2"), N("msq")], [N("var")])
                yield
                P.act(st["lnv"][:], st["var"][:], AF.Ln, [N("var")], [N("lnv")], bias=GN_EPS)
                P.act(st["rstd"][:], st["lnv"][:], AF.Exp, [N("lnv")], [N("rstd")], scale=-0.5)
                P.tt("gpsimd", yn[:], ysb[:], bc(st["mean"][:]), ALU.subtract, [N("ysb"), N("mean")], [N("yn")])
                yield
                P.tt("vector", yn[:], yn[:], bc(st["rstd"][:]), ALU.mult, [N("yn"), N("rstd")], [N("yn")])
                yield
                P.tt("gpsimd", yn[:], yn[:], lnw[:], ALU.mult, [N("yn"), "lnw"], [N("yn")])
                yield
                P.tt("vector", yn[:], yn[:], lnb[:], ALU.add, [N("yn"), "lnb"], [N("yn")])
                yield
                P.tt("gpsimd", yn[:], yn[:], bv[:], ALU.add, [N("yn"), N("bv")], [N("yn")])
                yield
                P.tt("vector", ob[:], yn[:], big["gst"][b][:, :, us], ALU.mult, [N("yn"), f"gst{b}"], [N("ob")])
                yield
                ptb = ptr[:].bitcast(BF16)
                for oc in range(8):
                    P.tr(ptb[0:64, oc * 128:(oc + 1) * 128], ob[:, oc, :], identb[:], [N("ob"), "identb"], ["ptr"])
                P.cp("scalar", OT[:, :, us], ptb[0:64, 0:1024].rearrange("p (h t) -> p h t", t=64), ["ptr"], ["OT"])
                yield

            def chain_all():
                for u in range(3, -1, -1):
                    chain_part(u)
                    yield

            jobs = [read_part(u) for u in range(3, -1, -1)]
            cgen = chain_all()
            next(cgen)
            active = []
            started = 0
            while jobs or active:
                while jobs and len(active) < 2:
                    if started >= 1:
                        try:
                            next(cgen)
                        except StopIteration:
                            pass
                    active.append(jobs.pop(0))
                    started += 1
                for gen in list(active):
                    try:
                        next(gen)
                    except StopIteration:
                        active.remove(gen)
            for oc in range(8):
                j = oc % 2
                for h in range(16):
                    P.mm(pw[j][:, 0:256], wo[:, h, oc * 128:(oc + 1) * 128], OT[:, h, :], h == 0, h == 15, ["wo", "OT"], [f"pw{j}"])
                P.stt(xt[b][:, oc, :], pw[j][:, 0:256], gates[oc], xt[b][:, oc, :], ALU.mult, ALU.add, [f"pw{j}", f"xt{b}", "modv"], [f"xt{b}"])
            P.dma("sync", fm(xa[:, c0:c0 + 256]), xt[b][:], reads=[f"xt{b}"], writes=[("xa", k)], sem=f"xt{b}")


def stage_qkv(P, io, G, hb, qtd, Kz, VA):
    vec, bones, perm = G["vec"], G["bones"], G["perm"]
    with P.phase("qkv"):
        wq = P.sb([128, 8, 1024], BF16)
        wkd = P.sb([128, 8, 512], BF16)
        wv = P.sb([128, 8, 256], BF16)
        P.dma("gpsimd", wq[:], fm(io["attn_wq"]), writes=["wq"], sem="wq")
        P.dma("gpsimd", wkd[:], fm(io["attn_wkd"]), writes=["wkd"], sem="wkd")
        P.dma("gpsimd", wv[:], fm(io["attn_wv"]), writes=["wv"], sem="wv")
        ht = [P.sb([128, 8, 512], BF16) for _ in range(2)]
        cs = [P.sb([128, 512], F32) for _ in range(2)]
        sn = [P.sb([128, 512], F32) for _ in range(2)]
        NB = 2
        qf = [P.sb([128, 512], F32) for _ in range(NB)]
        sqb = [P.sb([128, 512], BF16) for _ in range(NB)]
        lnv = [P.sb([128, 512], F32) for _ in range(NB)]
        rstd = [P.sb([128, 512], F32) for _ in range(NB)]
        qh = [P.sb([128, 512], F32) for _ in range(NB)]
        qhb = [P.sb([128, 512], BF16) for _ in range(NB)]
        t1 = [P.sb([128, 512], F32) for _ in range(NB)]
        t2 = [P.sb([128, 512], F32) for _ in range(NB)]
        qst = [P.sb([128, 8, 512], BF16) for _ in range(2)]
        pp = [P.ps([128, 512], F32) for _ in range(6)]
        cnt = [0, 0]

        def nxt():
            cnt[0] += 1
            return cnt[0] % 6

        P.memset("gpsimd", VA[:], 0.0, ["VA0"])
        P.memset("gpsimd", VA[:].rearrange("p k (j x) -> p k j x", x=65)[:, :, 0:5, 64:65], 1.0, ["VA0"])
        P.memset("gpsimd", Kz[0][64:128, :, :], 0.0, ["Kz0z"])
        P.memset("gpsimd", Kz[1][0:64, :, :], 0.0, ["Kz1z"])
        tiles = ALL_TILES

        def load(i):
            c0, tw, isctx = tiles[i]
            b = i % 2
            P.dma("sync", ht[b][:, :, :tw], fm(hb[:, c0:c0 + tw]), writes=[f"ht{b}"], sem=f"ht{b}")
            if not isctx:
                P.dma("sync", cs[b][:, :tw], io["cosT"][:, c0:c0 + tw], writes=[f"cs{b}"], sem=f"cs{b}")
                P.dma("sync", sn[b][:, :tw], io["sinT"][:, c0:c0 + tw], writes=[f"sn{b}"], sem=f"sn{b}")

        def normrope(wcols, nscal, dsts, b, tw, isctx, wname, dres="dstqk"):
            cnt[1] += 1
            n = cnt[1] % NB
            i = nxt()
            for c in range(8):
                P.mm(pp[i][:, :tw], wcols(c), ht[b][:, c, :tw], c == 0, c == 7, [wname, f"ht{b}"], [f"pp{i}"])
            P.cp("scalar", qf[n][:, :tw], pp[i][:, :tw], [f"pp{i}"], [f"qf{n}"])
            P.act(sqb[n][:, :tw], qf[n][:, :tw], AF.Square, [f"qf{n}"], [f"sqb{n}"])
            yield
            i = nxt()
            P.mm(pp[i][:, :tw], bones[:], sqb[n][:, :tw], True, True, ["bones", f"sqb{n}"], [f"pp{i}"])
            P.act(lnv[n][:, :tw], pp[i][:, :tw], AF.Ln, [f"pp{i}"], [f"lnv{n}"], bias=1e-6, scale=1.0 / 64)
            P.act(rstd[n][:, :tw], lnv[n][:, :tw], AF.Exp, [f"lnv{n}"], [f"rstd{n}"], scale=-0.5)
            yield
            P.stt(qh[n][:, :tw], qf[n][:, :tw], nscal, rstd[n][:, :tw], ALU.mult, ALU.mult, [f"qf{n}", f"rstd{n}", "vec"], [f"qh{n}"])
            if isctx:
                for dst, sl in dsts:
                    P.cp("gpsimd", dst, qh[n][sl, :tw], [f"qh{n}"], [dres])
                return
            P.cp("gpsimd", qhb[n][:, :tw], qh[n][:, :tw], [f"qh{n}"], [f"qhb{n}"])
            yield
            i = nxt()
            P.mm(pp[i][:, :tw], perm[:], qhb[n][:, :tw], True, True, ["perm", f"qhb{n}"], [f"pp{i}"])
            P.tt("gpsimd", t1[n][:, :tw], qh[n][:, :tw], cs[b][:, :tw], ALU.mult, [f"qh{n}", f"cs{b}"], [f"t1{n}"])
            P.tt("vector", t2[n][:, :tw], pp[i][:, :tw], sn[b][:, :tw], ALU.mult, [f"pp{i}", f"sn{b}"], [f"t2{n}"])
            yield
            for dst, sl in dsts:
                P.tt("gpsimd", dst, t1[n][sl, :tw], t2[n][sl, :tw], ALU.add, [f"t1{n}", f"t2{n}"], [dres])

        ALLP = slice(0, 128)
        load(0)
        for i, (c0, tw, isctx) in enumerate(tiles):
            b = i % 2
            if i + 1 < len(tiles):
                load(i + 1)
            jobs = []
            if not isctx:
                for oc in range(8):
                    jobs.append(normrope(lambda c, oc=oc: wq[:, c, oc * 128:(oc + 1) * 128], vec[:, 19, oc:oc + 1], [(qst[b][:, oc, :tw], ALLP)], b, tw, False, "wq",
                                         dres=(f"qst{b}", oc)))
            for g in range(4):
                jobs.append(normrope(lambda c, g=g: wkd[:, c, g * 128:(g + 1) * 128], vec[:, 20, 0:1],
                                     [(Kz[0][0:64, g, c0:c0 + tw], slice(0, 64)), (Kz[1][64:128, g, c0:c0 + tw], slice(64, 128))], b, tw, isctx, "wkd"))

            def vjob():
                for sub in range(tw // 128):
                    kt = c0 // 128 + sub
                    j = nxt()
                    for c in range(8):
                        P.mm(pp[j][:, 0:256], ht[b][:, c, sub * 128:(sub + 1) * 128], wv[:, c, :], c == 0, c == 7, ["wv", f"ht{b}"], [f"pp{j}"])
                    P.cp("scalar", VA[:, kt, 65:325].rearrange("p (g x) -> p g x", x=65)[:, :, 0:64],
                         pp[j][:, 0:256].rearrange("p (g d) -> p g d", d=64), [f"pp{j}", "VA0"], [("VA", kt)])
                    yield

            jobs.append(vjob())
            active = []
            while jobs or active:
                while jobs and len(active) < 2:
                    active.append(jobs.pop(0))
                for gen in list(active):
                    try:
                        next(gen)
                    except StopIteration:
                        active.remove(gen)
            if not isctx:
                P.dma("sync", fm(qtd[:, c0:c0 + tw]), qst[b][:, :, :tw], reads=[(f"qst{b}", oc) for oc in range(8)], writes=[("qtd", i)], sem=f"qst{b}")


def stage_attn(P, io, G, qtd, Kz, VA, xa):
    with P.phase("attn"):
        wo = P.sb([128, 8, 1024], BF16)
        P.dma("gpsimd", wo[:], fm(io["attn_wo"]), writes=["wo"], sem="wo")
        sel = P.sb([128, 2, 128], F32)
        P.dma("sync", sel[:], io["c_sel"], writes=["sel"], sem="sel")
        PT = [P.sb([128, 1024], BF16) for _ in range(3)]
        osb = [P.sb([128, 512], F32) for _ in range(2)]
        rb = [P.sb([128, 512], F32) for _ in range(2)]
        xt = P.sb([128, 8, 512], F32)
        QB = [P.sb([128, 8, 512], BF16) for _ in range(2)]
        psS = [P.ps([128, 1024], F32) for _ in range(2)]
        psO = [P.ps([128, 512], F32) for _ in range(2)]
        psB = P.ps([128, 512], F32)
        pX = [P.ps([128, 512], F32) for _ in range(1)]
        _, _, gates = mod_scalars(G, 1, 0, False)
        for k in range(2):
            P.memset("gpsimd", osb[k][:], 0.0, [f"osb{k}"])
        def loadq(qb):
            P.dma("sync", QB[qb % 2][:], fm(qtd[:, qb * 512:(qb + 1) * 512]), writes=[("QT", h, qb) for h in range(16)], sem=f"QB{qb % 2}")

        loadq(0)
        for qb in range(8):
            qsl = slice(qb * 512, (qb + 1) * 512)
            QT = QB[qb % 2]
            if qb + 1 < 8:
                loadq(qb + 1)
            P.dma("sync", xt[:], fm(xa[:, qsl]), writes=["xt"], sem="xt")
            steps = [(h, kp) for h in range(16) for kp in range(17)]

            def S(i):
                h, kp = steps[i]
                g, oc, h2 = h // 4, h // 2, h % 2
                for e_ in range(2):
                    kt = 2 * kp + e_
                    P.mm(psS[i % 2][:, e_ * 512:(e_ + 1) * 512], Kz[h2][:, g, kt * 128:(kt + 1) * 128], QT[:, oc, :], True, True,
                         ["Kz", ("QT", 2 * oc, qb), ("QT", 2 * oc + 1, qb)], [f"psS{i % 2}"])

            def epi_a(h):
                o = h % 2
                P.cp("vector", osb[o][:], psO[o][:], [f"psO{o}"], [f"osb{o}"])

            def epi_b(h):
                oc, h2, o = h // 2, h % 2, h % 2
                hs = slice(h2 * 64, h2 * 64 + 64)
                P.mm(psB[:, :], sel[:, h2, :], osb[o][:], True, True, ["sel", f"osb{o}"], ["psB"])
                P.act(rb[o][hs, :], psB[hs, :], AF.Ln, ["psB"], [f"rb{o}"])
                P.act(rb[o][hs, :], rb[o][hs, :], AF.Exp, [f"rb{o}"], [f"rb{o}"], scale=-1.0)
                P.tt("gpsimd", QT[hs, oc, :], osb[o][hs, :], rb[o][hs, :], ALU.mult, [f"osb{o}", f"rb{o}"], [("QT", h, qb)])

            S(0)
            pend = {}
            for i, (h, kp) in enumerate(steps):
                g, h2, o = h // 4, h % 2, h % 2
                if i + 1 < len(steps):
                    S(i + 1)
                p_ = i % 3
                P.act(PT[p_][:], psS[i % 2][:, :], AF.Exp, [f"psS{i % 2}"], [f"PT{p_}"], scale=0.125)
                v0 = 65 + 65 * g if h2 == 0 else 1 + 65 * g
                for e_ in range(2):
                    kt = 2 * kp + e_
                    P.mm(psO[o][:, :], VA[:, kt, v0:v0 + 128], PT[p_][:, e_ * 512:(e_ + 1) * 512], kt == 0, kt == 33, [f"PT{p_}", "VA"], [f"psO{o}"])
                if kp == 16:
                    epi_a(h)
                    pend[i + 3] = h
                if i in pend:
                    epi_b(pend.pop(i))
            for k in sorted(pend):
                epi_b(pend[k])
            for oc in range(8):
                j = 0
                for c in range(8):
                    P.mm(pX[j][:, :], wo[:, c, oc * 128:(oc + 1) * 128], QT[:, c, :], c == 0, c == 7,
                         ["wo", ("QT", 2 * c, qb), ("QT", 2 * c + 1, qb)], [f"pX{j}"])
                P.stt(xt[:, oc, :], pX[j][:, :], gates[oc], xt[:, oc, :], ALU.mult, ALU.add, [f"pX{j}", "xt", "modv"], ["xt"])
            P.dma("sync", fm(xa[:, qsl]), xt[:], reads=["xt"], writes=[("xa", qb)], sem="xt")


IN_SHAPES = {
    "xin": [D, TT], "cvec": [128, 8, 2], "w_mod": [2, D, 6 * D], "b_mod": [2, 6 * D], "vecs": [128, NV, 8],
    "mlp_w1": [2, D, 4 * D], "mlp_w2": [2, 4 * D, D],
    "rwkv_wr": [D, D], "rwkv_wk": [D, D], "rwkv_wv": [D, D], "rwkv_wo": [D, D],
    "rwkv_w1": [2, D, 64], "rwkv_w2": [2, 64, D], "rwkv_a1": [2, D, 64], "rwkv_a2": [2, 64, D],
    "rwkv_g1": [D, 128], "rwkv_g2": [128, D], "lnw_st": [128, 8, 64], "lnb_st": [128, 8, 64],
    "attn_wq": [D, D], "attn_wkd": [D, 512], "attn_wv": [D, 256], "attn_wo": [D, D],
    "cosT": [128, T], "sinT": [128, T],
    "c_ident": [128, 128], "c_ones": [128, 128], "c_bones": [128, 128], "c_masks": [128, 4, 128],
    "c_perm": [128, 128], "c_rmask": [128, 256], "c_sel": [128, 2, 128],
}


class IO(dict):
    def __init__(self, nc):
        super().__init__()
        self.nc = nc
        self.used = []

    def __missing__(self, k):
        ap = self.nc.dram_tensor(k, IN_SHAPES[k], F32, kind="ExternalInput").ap()
        self[k] = ap
        self.used.append(k)
        return ap

    def scratch(self, name, shape, dtype):
        return self.nc.dram_tensor(name, list(shape), dtype, kind="Internal").ap()

    def output(self, name, shape, dtype=F32):
        return self.nc.dram_tensor(name, list(shape), dtype, kind="ExternalOutput").ap()


def build(stages="all", dbg=None):
    nc = bass.Bass("TRN2", target_bir_lowering=False)
    io = IO(nc)
    P = Prog(nc)
    G = {}
    outs = {}
    stage_init(P, io, G)
    xa = io.scratch("xa", [D, TT], F32)
    hb = io.scratch("hb", [D, TT], BF16)
    if stages == "t_mlp":
        outs["dbg_h"] = io.output("dbg_h", [D, TT], BF16)
        stage_norm(P, io, G, "n_t", io["xin"], ALL_TILES,
                   lambda ic: mod_scalars(G, 0, 1, ic)[0], lambda ic: mod_scalars(G, 0, 1, ic)[1],
                   lambda c0, tw, ic: fm(hb[:, c0:c0 + tw]), BF16)
        with P.phase("copy"):
            P.dma("sync", xa, io["xin"], writes=["xa"], sem="cpa")
            P.dma("sync", outs["dbg_h"], hb, writes=["o"], sem="cpb")
        stage_mlp(P, io, G, 0, ALL_TILES, xa, hb)
        outs["y"] = io.output("y", [D, TT])
        fin = [G["vec"][:, 4, c:c + 1] for c in range(8)]
        stage_norm(P, io, G, "final", xa, ALL_TILES, lambda ic: fin, lambda ic: None,
                   lambda c0, tw, ic: fm(outs["y"][:, c0:c0 + tw]), F32)
    if stages in ("all", "l0", "l1pre"):
        hp = io.scratch("hp", [D, 4608], F32)
        S = rw_scratch(io)
        with P.phase("zpad"):
            z = P.sb([128, 8, 64], F32)
            P.memset("vector", z[:], 0.0, ["z"])
            for k, o in enumerate((0, 64 + T, 4224, 4288 + C)):
                P.dma("sync", fm(hp[:, o:o + 64]), z[:], reads=["z"], writes=[("hpz", k)], sem=f"z{k}")

        def hdst(c0, tw, ic):
            o = 4288 if ic else 64 + c0
            return fm(hp[:, o:o + tw])

        def hbdst(c0, tw, ic):
            return fm(hb[:, c0:c0 + tw])

        def ms(l, kind, which):
            return lambda ic: mod_scalars(G, l, kind, ic)[which]

        stage_norm(P, io, G, "n_mix0", io["xin"], ALL_TILES, ms(0, 0, 0), ms(0, 0, 1), hdst, F32)
        stage_rwkv1(P, io, G, hp, S)
        stage_rwkv2(P, io, G, S, io["xin"], xa)
        stage_norm(P, io, G, "n_mlp0", xa, ALL_TILES, ms(0, 1, 0), ms(0, 1, 1), hbdst, BF16)
        stage_mlp(P, io, G, 0, ALL_TILES, xa, hb)
        if stages == "l0":
            outs["y"] = io.output("y", [D, TT])
            with P.phase("copyout"):
                P.dma("sync", outs["y"], xa, writes=["o"], sem="cpa")
        else:
            stage_norm(P, io, G, "n_mix1", xa, ALL_TILES, ms(1, 0, 0), ms(1, 0, 1), hbdst, BF16)
            with P.scope():
                QT = io.scratch("qtd", [D, T], BF16)
                Kz = [P.ssb([128, 4, TT], BF16, f"Kz{k}") for k in range(2)]
                VA = P.ssb([128, 34, 390], BF16, "VA")
                stage_qkv(P, io, G, hb, QT, Kz, VA)
                stage_attn(P, io, G, QT, Kz, VA, xa)
            if stages == "l1pre":
                outs["y"] = io.output("y", [D, TT])
                with P.phase("copyout"):
                    P.dma("sync", outs["y"], xa, writes=["o"], sem="cpa")
            else:
                stage_norm(P, io, G, "n_mlp1", xa, LAT_TILES, ms(1, 1, 0), ms(1, 1, 1), hbdst, BF16)
                stage_mlp(P, io, G, 1, LAT_TILES, xa, hb)
                outs["y"] = io.output("y", [D, T])
                fin = [G["vec"][:, 4, c:c + 1] for c in range(8)]
                stage_norm(P, io, G, "final", xa, LAT_TILES, lambda ic: fin, lambda ic: None,
                           lambda c0, tw, ic: fm(outs["y"][:, c0:c0 + tw]), F32)
    if stages == "t_rwkv":
        hp = io.scratch("hp", [D, 4608], F32)
        S = rw_scratch(io)
        with P.phase("zpad"):
            z = P.sb([128, 8, 64], F32)
            P.memset("vector", z[:], 0.0, ["z"])
            for k, o in enumerate((0, 64 + T, 4224, 4288 + C)):
                P.dma("sync", fm(hp[:, o:o + 64]), z[:], reads=["z"], writes=[("hpz", k)], sem=f"z{k}")
        def hdst(c0, tw, ic):
            o = 4288 if ic else 64 + c0
            return fm(hp[:, o:o + tw])
        stage_norm(P, io, G, "n_mix0", io["xin"], ALL_TILES,
                   lambda ic: mod_scalars(G, 0, 0, ic)[0], lambda ic: mod_scalars(G, 0, 0, ic)[1], hdst, F32)
        stage_rwkv1(P, io, G, hp, S)
        stage_rwkv2(P, io, G, S, io["xin"], xa)
        outs["y"] = io.output("y", [D, TT])
        with P.phase("copyout"):
            P.dma("sync", outs["y"], xa, writes=["o"], sem="cpa")
    P.close()
    return nc, io.used, list(outs.keys()), P


def fmv(v):
    return np.ascontiguousarray(np.asarray(v, np.float32).reshape(8, 128).T)


def host_consts():
    c = {}
    c["c_ident"] = np.eye(128, dtype=np.float32)
    c["c_ones"] = np.ones((128, 128), np.float32)
    blk = np.zeros((128, 128), np.float32)
    blk[:64, :64] = 1
    blk[64:, 64:] = 1
    c["c_bones"] = blk
    i = np.arange(64)
    us = (i[:, None] < i[None, :]).astype(np.float32)
    ui = (i[:, None] <= i[None, :]).astype(np.float32)
    m = np.zeros((128, 4, 128), np.float32)
    for k, mk in enumerate([us, ui, us.T, ui.T]):
        m[:64, k, :64] = mk
        m[64:, k, 64:] = mk
    c["c_masks"] = m
    Pm = np.zeros((128, 128), np.float32)
    for d in range(128):
        if d % 32 < 16:
            Pm[d, d + 16] = -1.0
        else:
            Pm[d, d - 16] = 1.0
    c["c_perm"] = np.ascontiguousarray(Pm.T)
    sel = np.zeros((128, 2, 128), np.float32)
    sel[64, 0, :] = 1.0
    sel[63, 1, :] = 1.0
    c["c_sel"] = sel
    rm = np.ones((128, 256), np.float32)
    rm[:, ::64] = 0
    c["c_rmask"] = rm
    t = np.arange(T)
    row = (t // 64).astype(np.float32)
    col = (t % 64).astype(np.float32)
    freqs = (np.float32(10000.0) ** (-np.arange(0, 32, 2, dtype=np.float32) / np.float32(32))).astype(np.float32)
    ang = np.zeros((64, T), np.float32)
    for d in range(64):
        pos = row if d < 32 else col
        ang[d] = pos * freqs[d % 16]
    c["cosT"] = np.ascontiguousarray(np.concatenate([np.cos(ang), np.cos(ang)], 0).astype(np.float32))
    c["sinT"] = np.ascontiguousarray(np.concatenate([np.sin(ang), np.sin(ang)], 0).astype(np.float32))
    return c


def host_inputs(inp, b):
    f = lambda k: np.asarray(inp[k], np.float32)
    d = {}
    d["xin"] = np.ascontiguousarray(np.concatenate([f("x")[b].T, f("ctx")[b].T], axis=1))
    d["cvec"] = np.ascontiguousarray(np.stack([fmv(f("c")[b]), fmv(f("c_ctx"))], axis=-1))
    return d


def host_shared(inp):
    f = lambda k: np.asarray(inp[k], np.float32)
    s = dict(host_consts())
    s["w_mod"] = f("w_mod")
    s["b_mod"] = f("b_mod")
    vl = [f("norm_mix")[0], f("norm_mix")[1], f("norm_mlp")[0], f("norm_mlp")[1], f("final_norm")]
    vl += [f("rwkv_mu")[0, j] for j in range(6)]
    vl += [f("rwkv_w0")[0, 0], f("rwkv_w0")[0, 1], f("rwkv_a0")[0, 0], f("rwkv_a0")[0, 1]]
    vl += [f("rwkv_k_k")[0], f("rwkv_k_a")[0], np.zeros(D, np.float32), f("rwkv_r_k")[0].reshape(-1)]
    vl += [np.tile(f("attn_q_norm")[0], 16), np.tile(f("attn_k_norm")[0], 16)]
    assert len(vl) == NV
    s["vecs"] = np.ascontiguousarray(np.stack([fmv(v) for v in vl], axis=1))
    s["mlp_w1"] = f("mlp_w1")
    s["mlp_w2"] = f("mlp_w2")
    for k in ("wr", "wk", "wv", "wo", "w1", "w2", "a1", "a2", "g1", "g2"):
        s["rwkv_" + k] = f("rwkv_" + k)[0]
    lw = f("rwkv_ln_w")[0].reshape(8, 2, 64)
    lb = f("rwkv_ln_b")[0].reshape(8, 2, 64)
    s["lnw_st"] = np.ascontiguousarray(np.repeat(lw.transpose(1, 0, 2), 64, axis=0))
    s["lnb_st"] = np.ascontiguousarray(np.repeat(lb.transpose(1, 0, 2), 64, axis=0))
    wqkv = f("attn_wqkv")[0]
    s["attn_wq"] = np.ascontiguousarray(wqkv[:, :1024])
    wk = wqkv[:, 1024:1280].reshape(D, 4, 64)
    s["attn_wkd"] = np.ascontiguousarray(np.concatenate([wk, wk], axis=2).reshape(D, 512))
    s["attn_wv"] = np.ascontiguousarray(wqkv[:, 1280:1536])
    s["attn_wo"] = f("attn_wo")[0]
    return s


_CACHE = {}


def kernel(**inputs):
    if "prog" not in _CACHE:
        _CACHE["prog"] = build("all")
    nc, used, outnames, _ = _CACHE["prog"]
    shared = host_shared(inputs)
    in_maps = []
    for b in range(NCORES):
        hi = host_inputs(inputs, b)
        hi.update(shared)
        in_maps.append({k: hi[k] for k in used})
    res = run_bass_kernel_spmd(nc, in_maps, core_ids=list(range(NCORES)))
    out = np.stack([np.ascontiguousarray(res.results[b]["y"].T) for b in range(NCORES)], axis=0)
    return out.astype(np.float32)
```

```python
from contextlib import ExitStack, contextmanager
import re as re_mod
import numpy as np
import concourse.bass as bass
import concourse.mybir as mybir
from concourse.bass_utils import run_bass_kernel_spmd

F32 = mybir.dt.float32
BF16 = mybir.dt.bfloat16
AF = mybir.ActivationFunctionType
ALU = mybir.AluOpType
AX = mybir.AxisListType

D = 1024
T = 4096
C = 256
TT = T + C
NCORES = 8
C0 = float(np.exp(-0.5))
NV = 21
ENGS = ("tensor", "vector", "scalar", "gpsimd", "sync")


class Prog:
    def __init__(self, nc):
        self.nc = nc
        self.ges = ExitStack()
        self.sems = {}
        self.cnt = {}
        self.dpool = {False: [], True: []}
        self.seen = {e: {} for e in ENGS}
        self.n = 0
        self.pes = None
        self.total_ops = 0

    def _alloc(self, es, fn, shape, dtype, name):
        self.n += 1
        return es.enter_context(fn(name or f"t{self.n}", list(shape), dtype))

    def gsb(self, shape, dtype, name=None):
        return self._alloc(self.ges, self.nc.sbuf_tensor, shape, dtype, name)

    def sb(self, shape, dtype, name=None):
        return self._alloc(self.pes, self.nc.sbuf_tensor, shape, dtype, name)

    @contextmanager
    def scope(self):
        self.ses = ExitStack()
        yield self
        self.ses.close()
        self.ses = None

    def ssb(self, shape, dtype, name=None):
        return self._alloc(self.ses, self.nc.sbuf_tensor, shape, dtype, name)

    def ps(self, shape, dtype, name=None):
        return self._alloc(self.pes, self.nc.psum_tensor, shape, dtype, name)

    @contextmanager
    def phase(self, name):
        self.ops = []
        self.last_w = {}
        self.readers = {}
        self.last_dma = {}
        self.pes = ExitStack()
        self.pname = name
        yield self
        self._emit()
        self.pes.close()
        self.pes = None

    _PSUM_RE = re_mod.compile(r"^(pp|pa|pb|pq|pf|ps\w*|pX|py|pS|ptr|pw)\d*$")

    def _deps(self, reads, writes):
        extra = tuple(r for r in reads if isinstance(r, str) and self._PSUM_RE.match(r) and r not in writes)
        if extra:
            writes = tuple(writes) + extra
        deps = {}
        for r in reads:
            if r in self.last_w:
                deps.setdefault(self.last_w[r], set()).add("RAW")
        for w in writes:
            if w in self.last_w:
                deps.setdefault(self.last_w[w], set()).add("WAW")
            for rd in self.readers.get(w, ()):
                deps.setdefault(rd, set()).add("WAR")
        idx = len(self.ops)
        for r in reads:
            self.readers.setdefault(r, []).append(idx)
        for w in writes:
            self.last_w[w] = idx
            self.readers[w] = []
        return deps

    def op(self, eng, fn, reads=(), writes=()):
        deps = self._deps(tuple(reads), tuple(writes))
        self.ops.append(dict(eng=eng, fn=fn, deps=deps, dma=None))
        return len(self.ops) - 1

    def dma(self, queue, out, in_, reads=(), writes=(), sem=None):
        deps = self._deps(tuple(reads), tuple(writes))
        prev = self.last_dma.get(sem)
        if prev is not None:
            deps.setdefault(prev, set()).add("SER")
        idx = len(self.ops)
        self.last_dma[sem] = idx
        self.ops.append(dict(eng=queue, fn=lambda e: e.dma_start(out=out, in_=in_), deps=deps, dma=sem))
        return idx

    def _emit(self):
        nc = self.nc
        ops = self.ops
        if self.last_dma:
            ops.append(dict(eng="sync", fn=None, deps={i: {"FIN"} for i in self.last_dma.values()}, dma=None))
        self.total_ops += len(ops)

        def needs_wait(x, d, kinds):
            if d["dma"] is not None or x["dma"] is not None:
                return True
            if d["eng"] != x["eng"]:
                return True
            if x["eng"] == "tensor":
                return False
            return bool(kinds & {"RAW", "FIN"})

        signal = [False] * len(ops)
        for x in ops:
            for di, kinds in x["deps"].items():
                d = ops[di]
                if d["dma"] is None and needs_wait(x, d, kinds):
                    signal[di] = True
        dkeys = {}
        nk = {False: 0, True: 0}
        for o in ops:
            if o["dma"] is not None and o["dma"] not in dkeys:
                sw = o["eng"] == "gpsimd"
                dkeys[o["dma"]] = (sw, nk[sw])
                nk[sw] += 1
        for sw in (False, True):
            while len(self.dpool[sw]) < nk[sw]:
                h = self.ges.enter_context(nc.semaphore(f"dq{int(sw)}_{len(self.dpool[sw])}"))
                self.dpool[sw].append([h, 0])
        for e in ENGS:
            if e not in self.sems:
                self.sems[e] = self.ges.enter_context(nc.semaphore(f"e_{e}"))
        token = [None] * len(ops)
        for i, o in enumerate(ops):
            if o["dma"] is not None:
                dk = dkeys[o["dma"]]
                slot = self.dpool[dk[0]][dk[1]]
                slot[1] += 16
                token[i] = (("d", dk), slot[1])
            elif signal[i]:
                self.cnt[o["eng"]] = self.cnt.get(o["eng"], 0) + 1
                token[i] = (("e", o["eng"]), self.cnt[o["eng"]])
        per_eng = {e: [] for e in ENGS}
        for i, o in enumerate(ops):
            per_eng[o["eng"]].append(i)

        def semh(key):
            return self.dpool[key[1][0]][key[1][1]][0] if key[0] == "d" else self.sems[key[1]]

        def run(engname, eng):
            seen = self.seen[engname]
            for i in per_eng[engname]:
                o = ops[i]
                waits = {}
                for di, kinds in o["deps"].items():
                    d = ops[di]
                    if not needs_wait(o, d, kinds):
                        continue
                    key, val = token[di]
                    if waits.get(key, 0) < val:
                        waits[key] = val
                for key, val in waits.items():
                    if seen.get(key, 0) >= val:
                        continue
                    seen[key] = val
                    eng.wait_ge(semh(key), val)
                if o["fn"] is None:
                    continue
                ins = o["fn"](eng)
                if o["dma"] is not None:
                    ins.then_inc(semh(token[i][0]), 16)
                elif signal[i]:
                    ins.then_inc(self.sems[engname], 1)

        with nc.Block() as block:
            @block.sync
            def _(e):
                run("sync", e)

            @block.tensor
            def _(e):
                run("tensor", e)

            @block.vector
            def _(e):
                run("vector", e)

            @block.scalar
            def _(e):
                run("scalar", e)

            @block.gpsimd
            def _(e):
                run("gpsimd", e)

    def close(self):
        self.ges.close()

    def mm(self, out, lhsT, rhs, start, stop, r, w):
        self.op("tensor", lambda e: e.matmul(out, lhsT=lhsT, rhs=rhs, start=start, stop=stop), r, w)

    def tr(self, out, in_, ident, r, w):
        self.op("tensor", lambda e: e.transpose(out, in_, ident), r, w)

    def tt(self, eng, out, in0, in1, op, r, w):
        self.op(eng, lambda e: e.tensor_tensor(out=out, in0=in0, in1=in1, op=op), r, w)

    def ts(self, eng, out, in0, s1, s2, op0, op1, r, w):
        if op1 is None:
            self.op(eng, lambda e: e.tensor_scalar(out=out, in0=in0, scalar1=s1, scalar2=None, op0=op0), r, w)
        else:
            self.op(eng, lambda e: e.tensor_scalar(out=out, in0=in0, scalar1=s1, scalar2=s2, op0=op0, op1=op1), r, w)

    def stt(self, out, in0, scalar, in1, op0, op1, r, w):
        self.op("vector", lambda e: e.scalar_tensor_tensor(out=out, in0=in0, scalar=scalar, in1=in1, op0=op0, op1=op1), r, w)

    def act(self, out, in_, func, r, w, bias=None, scale=None):
        kw = {}
        if bias is not None:
            kw["bias"] = bias
        if scale is not None:
            kw["scale"] = scale
        self.op("scalar", lambda e: e.activation(out=out, in_=in_, func=func, **kw), r, w)

    def cp(self, eng, out, in_, r, w):
        if eng == "scalar":
            self.op(eng, lambda e: e.activation(out=out, in_=in_, func=AF.Copy), r, w)
        else:
            self.op(eng, lambda e: e.tensor_copy(out=out, in_=in_), r, w)

    def memset(self, eng, ap, val, w):
        self.op(eng, lambda e: e.memset(ap, val), (), w)


def fm(ap2d):
    return ap2d.rearrange("(c p) n -> p c n", p=128)


LAT_TILES = [(i * 512, 512, False) for i in range(8)]
ALL_TILES = LAT_TILES + [(T, 256, True)]


def stage_init(P, io, G):
    nc = P.nc
    G["identf"] = P.gsb([128, 128], F32, "identf")
    G["identb"] = P.gsb([128, 128], BF16, "identb")
    G["onesb"] = P.gsb([128, 128], BF16, "onesb")
    G["bones"] = P.gsb([128, 128], BF16, "bones")
    G["masks"] = P.gsb([128, 4, 128], BF16, "masks")
    G["perm"] = P.gsb([128, 128], BF16, "perm")
    G["rmask"] = P.gsb([128, 256], F32, "rmask")
    G["vec"] = P.gsb([128, NV, 8], F32, "vec")
    G["modv"] = P.gsb([128, 2, 6, 8, 2], F32, "modv")
    G["gg"] = P.gsb([128, 2, 2, 8, 2], F32, "gg")
    with P.phase("init"):
        P.dma("sync", G["identf"][:], io["c_ident"], writes=["identf"], sem="identf")
        P.dma("sync", G["rmask"][:], io["c_rmask"], writes=["rmask"], sem="rmask")
        P.dma("sync", G["vec"][:], io["vecs"], writes=["vec"], sem="vec")
        P.dma("gpsimd", G["identb"][:], io["c_ident"], writes=["identb"], sem="identb")
        P.dma("gpsimd", G["onesb"][:], io["c_ones"], writes=["onesb"], sem="onesb")
        P.dma("gpsimd", G["bones"][:], io["c_bones"], writes=["bones"], sem="bones")
        P.dma("gpsimd", G["masks"][:], io["c_masks"], writes=["masks"], sem="masks")
        P.dma("gpsimd", G["perm"][:], io["c_perm"], writes=["perm"], sem="perm")
        vec = G["vec"]
        P.ts("vector", vec[:, 17, :], vec[:, 16, :], -1.0, 1.0, ALU.mult, ALU.add, ["vec"], ["vec"])
        sv = P.sb([128, 8, 2], F32)
        svs = P.sb([128, 8, 2], F32)
        P.dma("sync", sv[:], io["cvec"], writes=["sv"], sem="sv")
        P.act(svs[:], sv[:], AF.Silu, ["sv"], ["svs"])
        brow = P.sb([2, 2 * 6144], F32)
        row = P.sb([2, 2 * 6144], F32)
        P.dma("sync", brow[:], io["b_mod"].rearrange("l n -> (l n)").partition_broadcast(2), writes=["brow"], sem="brow")
        wt = [P.sb([128, 8, 512], F32) for _ in range(2)]
        psr = [P.ps([128, 512], F32) for _ in range(2)]
        pst = P.ps([128, 512], F32)
        k = 0
        for l in range(2):
            for nb in range(12):
                b = k % 2
                k += 1
                P.dma("sync", wt[b][:], fm(io["w_mod"][l, :, nb * 512:(nb + 1) * 512]), writes=[f"wt{b}"], sem=f"wt{b}")
                for c in range(8):
                    P.mm(psr[b][0:2, :], svs[:, c, :], wt[b][:, c, :], c == 0, c == 7, ["svs", f"wt{b}"], [f"psr{b}"])
                o = l * 6144 + nb * 512
                P.tt("vector", row[:, o:o + 512], psr[b][0:2, :], brow[:, o:o + 512], ALU.add, [f"psr{b}", "brow"], ["row"])
        for l in range(2):
            for blk in range(48):
                o = l * 6144 + blk * 128
                P.tr(pst[:, l * 96 + blk * 2:l * 96 + blk * 2 + 2], row[0:2, o:o + 128], G["identf"][0:2, 0:2], ["row", "identf"], ["pst"])
        P.cp("vector", G["modv"][:].rearrange("p l m c j -> p (l m c j)"), pst[:, 0:192], ["pst"], ["modv"])
        modv, gg = G["modv"], G["gg"]
        for l in range(2):
            for kind in range(2):
                sc = modv[:, l, 1 + 3 * kind, :, :]
                nv = vec[:, (0 if kind == 0 else 2) + l, :].unsqueeze(2).broadcast_to([128, 8, 2])
                P.ts("vector", gg[:, l, kind, :, :], sc, 1.0, None, ALU.add, None, ["modv"], ["gg"])
                P.tt("vector", gg[:, l, kind, :, :], gg[:, l, kind, :, :], nv, ALU.mult, ["gg", "vec"], ["gg"])


def mod_scalars(G, l, kind, isctx):
    j = 1 if isctx else 0
    gains = [G["gg"][:, l, kind, c, j:j + 1] for c in range(8)]
    shifts = [G["modv"][:, l, 3 * kind, c, j:j + 1] for c in range(8)]
    gates = [G["modv"][:, l, 3 * kind + 2, c, j:j + 1] for c in range(8)]
    return gains, shifts, gates


def stage_norm(P, io, G, name, src, tiles, gains_fn, shifts_fn, dst_fn, out_dtype):
    with P.phase(name):
        xt = [P.sb([128, 8, 512], F32) for _ in range(2)]
        sq = P.sb([128, 8, 512], BF16)
        lnv = P.sb([128, 512], F32)
        rstd = P.sb([128, 512], F32)
        tmp = [P.sb([128, 512], F32) for _ in range(2)]
        ho = [P.sb([128, 8, 512], out_dtype) for _ in range(2)]
        ps = [P.ps([128, 512], F32) for _ in range(2)]

        def load(i):
            c0, tw, _ = tiles[i]
            b = i % 2
            P.dma("sync", xt[b][:, :, :tw], fm(src[:, c0:c0 + tw]), writes=[f"xt{b}"], sem=f"xt{b}")

        load(0)
        for i, (c0, tw, isctx) in enumerate(tiles):
            b = i % 2
            if i + 1 < len(tiles):
                load(i + 1)
            gains = gains_fn(isctx)
            shifts = shifts_fn(isctx)
            P.act(sq[:, :, :tw], xt[b][:, :, :tw], AF.Square, [f"xt{b}"], ["sq"])
            for c in range(8):
                P.mm(ps[b][:, :tw], G["onesb"][:], sq[:, c, :tw], c == 0, c == 7, ["sq", "onesb"], [f"ps{b}"])
            P.act(lnv[:, :tw], ps[b][:, :tw], AF.Ln, [f"ps{b}"], ["lnv"], bias=1e-6, scale=1.0 / D)
            P.act(rstd[:, :tw], lnv[:, :tw], AF.Exp, ["lnv"], ["rstd"], scale=-0.5)
            for c in range(8):
                if shifts is None:
                    P.stt(ho[b][:, c, :tw], xt[b][:, c, :tw], gains[c], rstd[:, :tw], ALU.mult, ALU.mult,
                          [f"xt{b}", "rstd", "vec", "gg"], [f"ho{b}"])
                else:
                    t = tmp[c % 2]
                    P.stt(t[:, :tw], xt[b][:, c, :tw], gains[c], rstd[:, :tw], ALU.mult, ALU.mult,
                          [f"xt{b}", "rstd", "vec", "gg"], [f"tmp{c % 2}"])
                    P.act(ho[b][:, c, :tw], t[:, :tw], AF.Identity, [f"tmp{c % 2}", "modv"], [f"ho{b}"], bias=shifts[c])
            P.dma("sync", dst_fn(c0, tw, isctx), ho[b][:, :, :tw], reads=[f"ho{b}"], writes=[("dst", i)], sem=f"ho{b}")


def stage_mlp(P, io, G, l, tiles, xa, hb):
    for half in range(2):
        with P.phase(f"mlp{l}{half}"):
            w1 = P.sb([128, 8, 2048], BF16)
            w2 = P.sb([128, 16, 1024], BF16)
            for q in range(2):
                P.dma("gpsimd", w1[:, :, q * 1024:(q + 1) * 1024],
                      fm(io["mlp_w1"][l, :, half * 2048 + q * 1024: half * 2048 + (q + 1) * 1024]), writes=["w1"], sem=f"w1{q}")
                P.dma("gpsimd", w2[:, q * 8:(q + 1) * 8, :],
                      io["mlp_w2"][l, half * 2048 + q * 1024: half * 2048 + (q + 1) * 1024, :].rearrange("(f p) n -> p f n", p=128),
                      writes=["w2"], sem=f"w2{q}")
            xt = [P.sb([128, 8, 512], F32) for _ in range(2)]
            ht = [P.sb([128, 8, 512], BF16) for _ in range(2)]
            h1 = P.sb([128, 16, 512], BF16)
            r1 = [P.sb([128, 512], F32) for _ in range(2)]
            ps = [P.ps([128, 512], F32) for _ in range(4)]

            def load(i):
                c0, tw, _ = tiles[i]
                b = i % 2
                P.dma("sync", ht[b][:, :, :tw], fm(hb[:, c0:c0 + tw]), writes=[f"ht{b}"], sem=f"ht{b}")
                P.dma("sync", xt[b][:, :, :tw], fm(xa[:, c0:c0 + tw]), reads=[("xa", i)], writes=[f"xt{b}"], sem=f"xt{b}")

            load(0)
            for i, (c0, tw, isctx) in enumerate(tiles):
                b = i % 2
                if i + 1 < len(tiles):
                    load(i + 1)
                _, _, gates = mod_scalars(G, l, 1, isctx)
                for fc in range(16):
                    pb = fc % 2
                    for c in range(8):
                        P.mm(ps[pb][:, :tw], w1[:, c, fc * 128:(fc + 1) * 128], ht[b][:, c, :tw], c == 0, c == 7,
                             ["w1", f"ht{b}"], [f"ps{pb}"])
                    P.act(r1[pb][:, :tw], ps[pb][:, :tw], AF.Relu, [f"ps{pb}"], [f"r1{pb}"])
                    P.tt("gpsimd", h1[:, fc, :tw], r1[pb][:, :tw], r1[pb][:, :tw], ALU.mult, [f"r1{pb}"], [("h1", fc)])
                for oc in range(8):
                    pb = 2 + oc % 2
                    for fc in range(16):
                        P.mm(ps[pb][:, :tw], w2[:, fc, oc * 128:(oc + 1) * 128], h1[:, fc, :tw], fc == 0, fc == 15,
                             ["w2", ("h1", fc)], [f"ps{pb}"])
                    P.stt(xt[b][:, oc, :tw], ps[pb][:, :tw], gates[oc], xt[b][:, oc, :tw], ALU.mult, ALU.add,
                          [f"ps{pb}", f"xt{b}", "modv"], [f"xt{b}"])
                P.dma("sync", fm(xa[:, c0:c0 + tw]), xt[b][:, :, :tw], reads=[f"xt{b}"], writes=[("xa", i)], sem=f"xt{b}")


RW_ORDER1 = [(True, 0)] + [(False, i) for i in range(16)]
RW_ORDER2 = [(True, 0)] + [(False, i) for i in range(15, -1, -1)]


def rw_scratch(io):
    S = {}
    S["yp"] = io.scratch("rw_yp", [17, 8, 128, 256], F32)
    S["sadd"] = io.scratch("rw_sadd", [17, 8, 128, 256], F32)
    S["vst"] = io.scratch("rw_vst", [17, 8, 128, 256], F32)
    S["gst"] = io.scratch("rw_gst", [17, 8, 128, 256], F32)
    S["gyb"] = io.scratch("rw_gyb", [17, 8, 128, 512], BF16)
    S["gsb"] = io.scratch("rw_gsb", [17, 8, 128, 512], BF16)
    S["gamb"] = io.scratch("rw_gamb", [17, 128, 32], F32)
    S["bon"] = io.scratch("rw_bon", [17, 128, 32], F32)
    return S


def stage_rwkv1(P, io, G, hp, S, dbg=None):
    vec, masks, identb, identf, bones, onesb, rmask = (G[k] for k in ("vec", "masks", "identb", "identf", "bones", "onesb", "rmask"))
    with P.phase("rwkv1"):
        wr = P.sb([128, 8, 1024], BF16)
        wk = P.sb([128, 8, 1024], BF16)
        wv = P.sb([128, 8, 1024], BF16)
        for w, nm in ((wr, "rwkv_wr"), (wk, "rwkv_wk"), (wv, "rwkv_wv")):
            P.dma("gpsimd", w[:], fm(io[nm]), writes=[nm], sem=nm)
        lw1 = P.sb([128, 8, 128], BF16)
        la1 = P.sb([128, 8, 128], BF16)
        g1 = P.sb([128, 8, 128], BF16)
        for d in range(2):
            P.dma("gpsimd", lw1[:, :, d * 64:(d + 1) * 64], io["rwkv_w1"][d].rearrange("(c p) j -> p c j", p=128), writes=["lw1"], sem=f"lw1{d}")
            P.dma("gpsimd", la1[:, :, d * 64:(d + 1) * 64], io["rwkv_a1"][d].rearrange("(c p) j -> p c j", p=128), writes=["la1"], sem=f"la1{d}")
        P.dma("gpsimd", g1[:], io["rwkv_g1"].rearrange("(c p) j -> p c j", p=128), writes=["g1"], sem="g1")
        w2s = P.sb([128, 1024], BF16)
        a2s = P.sb([128, 1024], BF16)
        g2 = P.sb([128, 1024], BF16)
        P.dma("gpsimd", w2s[:], io["rwkv_w2"].rearrange("d j f -> (d j) f"), writes=["w2s"], sem="w2s")
        P.dma("gpsimd", a2s[:], io["rwkv_a2"].rearrange("d j f -> (d j) f"), writes=["a2s"], sem="a2s")
        P.dma("gpsimd", g2[:], io["rwkv_g2"], writes=["g2"], sem="g2")

        hh = P.sb([128, 8, 384], F32)
        xx = P.sb([128, 8, 256], F32)
        xr = P.sb([128, 8, 256], BF16)
        xk = P.sb([128, 8, 256], BF16)
        xv = P.sb([128, 8, 256], BF16)
        xrot = P.sb([128, 8, 256], BF16)
        lwt = P.sb([128, 256], BF16)
        lat = P.sb([128, 256], BF16)
        sg = P.sb([128, 256], BF16)
        f32t = {}
        for nm in ("r", "k", "sw0", "sw1", "ag0", "ag1", "kq", "lnv", "rs", "kkn", "fac", "kd0", "kd1", "b0", "b1",
                   "L", "Lx", "Lb", "E1", "E2", "E3", "ks"):
            f32t[nm] = P.sb([128, 256], F32, "t_" + nm)
        sqb = P.sb([128, 256], BF16)
        RK = P.sb([128, 4, 2, 64], BF16)
        VTbd = P.sb([128, 4, 128], F32)
        GTbd = P.sb([128, 4, 128], F32)
        Vf = P.sb([128, 4, 64], F32)
        Gf = P.sb([128, 4, 64], F32)
        YPs = P.sb([128, 4, 64], F32)
        SAs = P.sb([128, 4, 64], F32)
        gamb_t = P.sb([128, 8, 4], F32)
        bon_t = P.sb([128, 8, 4], F32)
        Sf = P.sb([128, 8, 64], BF16)
        ARq = [[P.sb([128, 4, 2, 128], BF16, f"AR{q}{d}") for d in range(2)] for q in range(2)]
        KTq = [[P.sb([128, 4, 128], BF16, f"KT{q}{d}") for d in range(2)] for q in range(2)]
        BTq = [[P.sb([128, 4, 128], BF16, f"BT{q}{d}") for d in range(2)] for q in range(2)]
        Vbq = [P.sb([128, 4, 64], BF16, f"Vb{q}") for q in range(3)]
        gamq = [[P.sb([128, 4], F32, f"gam{q}{d}") for d in range(2)] for q in range(3)]
        inv = []
        for d in range(2):
            st = {}
            for nm, shp in (("Atok", [128, 4, 128]), ("Btok", [128, 4, 128]), ("MQ", [128, 4, 256]), ("MWa", [128, 4, 2, 128]),
                            ("MWb", [128, 4, 2, 128]), ("MTa", [128, 4, 128]), ("MTb", [128, 4, 128])):
                st[nm] = P.sb(shp, BF16, f"i{d}_{nm}")
            inv.append(st)
        fin = []
        for q in range(2):
            row = []
            for d in range(2):
                st = {}
                for nm, shp in (("Ktok", [128, 4, 128]), ("NP", [128, 4, 256]), ("XW", [128, 4, 256]), ("NVb", [128, 4, 64]),
                                ("GY", [128, 4, 128]), ("GS", [128, 4, 128])):
                    st[nm] = P.sb(shp, BF16, f"f{q}{d}_{nm}")
                row.append(st)
            fin.append(row)
        ppt = [P.ps([128, 512], F32) for _ in range(2)]
        pp = [t_[:, 0:256] for t_ in ppt]
        pf = P.ps([128, 512], F32)
        pb = [P.ps([128, 512], F32) for _ in range(5)]
        cnt = {"pp": 0, "pb": 0}
        nmod = {"pp": 2, "pb": 5}

        def nxt(kind):
            i = cnt[kind] % nmod[kind]
            cnt[kind] += 1
            return i

        for q in range(2):
            for d in range(2):
                P.memset("gpsimd", ARq[q][d][:], 0.0, [f"AR{q}{d}"])
                P.memset("gpsimd", KTq[q][d][:], 0.0, [f"KT{q}{d}"])
                P.memset("gpsimd", BTq[q][d][:], 0.0, [f"BT{q}{d}"])
        P.memset("gpsimd", RK[:], 0.0, ["RK"])
        P.memset("gpsimd", VTbd[:], 0.0, ["VTbd"])
        P.memset("gpsimd", GTbd[:], 0.0, ["GTbd"])
        P.memset("gpsimd", Sf[:], 0.0, [("Sf", p) for p in range(8)])

        def v3(ap):
            return ap.rearrange("p (u s) -> p u s", s=64)

        def u128(ap):
            return ap.rearrange("p (u x) -> p u x", x=128)

        def load_hh(ti):
            isctx, idx = RW_ORDER1[ti]
            off = 4288 if isctx else 64 + 256 * idx
            P.dma("sync", hh[:], fm(hp[:, off - 64: off + 320]), writes=["hh"], sem="hh")

        def proj8(w_cols_fn, xb, bn, extra_r):
            i = nxt("pp")
            for c in range(8):
                P.mm(pp[i], w_cols_fn(c), xb[:, c, :], c == 0, c == 7, [(bn, c)] + extra_r, [f"pp{i}"])
            return i

        def tprep(ti):
            isctx, idx = RW_ORDER1[ti]
            hc = hh[:, :, 64:320]
            XXW = [("xx", c) for c in range(8)]
            if not isctx:
                h4 = hh[:, :, 64:320].rearrange("p c (r w) -> p c r w", w=64)
                x4 = xx[:].rearrange("p c (r w) -> p c r w", w=64)
                P.tt("vector", x4[:, 0:2, :, 1:64], h4[:, 0:2, :, 0:63], h4[:, 0:2, :, 1:64], ALU.subtract, ["hh"], XXW[0:2])
                P.ts("gpsimd", x4[:, 0:2, :, 0:1], h4[:, 0:2, :, 0:1], -1.0, 0.0, ALU.mult, ALU.add, ["hh"], [("xxe", 0)])
                P.tt("vector", x4[:, 2:4, :, 0:63], h4[:, 2:4, :, 1:64], h4[:, 2:4, :, 0:63], ALU.subtract, ["hh"], XXW[2:4])
                P.ts("gpsimd", x4[:, 2:4, :, 63:64], h4[:, 2:4, :, 63:64], -1.0, 0.0, ALU.mult, ALU.add, ["hh"], [("xxe", 1)])
                P.tt("gpsimd", xx[:, 4:6, :], hh[:, 4:6, 0:256], hh[:, 4:6, 64:320], ALU.subtract, ["hh"], XXW[4:6])
                P.tt("gpsimd", xx[:, 6:8, :], hh[:, 6:8, 128:384], hh[:, 6:8, 64:320], ALU.subtract, ["hh"], XXW[6:8])
            else:
                P.tt("vector", xx[:, 0:4, :], hh[:, 0:4, 63:319], hh[:, 0:4, 64:320], ALU.subtract, ["hh"], XXW[0:4] + [("xxe", 0)])
                P.tt("gpsimd", xx[:, 4:8, :], hh[:, 4:8, 65:321], hh[:, 4:8, 64:320], ALU.subtract, ["hh"], XXW[4:8] + [("xxe", 1)])
            yield

            def mk_xj(j, buf, bn):
                for c in range(8):
                    P.stt(buf[:, c, :], xx[:, c, :], vec[:, 5 + j, c:c + 1], hc[:, c, :], ALU.mult, ALU.add,
                          [("xx", c), ("xxe", 0), ("xxe", 1), "hh", "vec"], [(bn, c)])

            mk_xj(1, xrot, "xrot")
            yield
            i = proj8(lambda c: lw1[:, c, :], xrot, "xrot", ["lw1"])
            P.act(lwt[:], pp[i], AF.Tanh, [f"pp{i}"], ["lwt"])
            yield
            mk_xj(4, xrot, "xrot")
            yield
            i = proj8(lambda c: la1[:, c, :], xrot, "xrot", ["la1"])
            P.cp("scalar", lat[:], pp[i], [f"pp{i}"], ["lat"])
            yield
            mk_xj(5, xrot, "xrot")
            yield
            i = proj8(lambda c: g1[:, c, :], xrot, "xrot", ["g1"])
            P.act(sg[:], pp[i], AF.Sigmoid, [f"pp{i}"], ["sg"])
            yield
            mk_xj(0, xr, "xr")
            yield
            mk_xj(2, xk, "xk")
            yield
            mk_xj(3, xv, "xv")
            if ti + 1 < len(RW_ORDER1):
                load_hh(ti + 1)
            yield

        def prep(ti, oc, q, z):
            isctx, idx = RW_ORDER1[ti]
            tg = 16 if isctx else idx
            cs = slice(oc * 128, (oc + 1) * 128)
            t = f32t
            AR, KT, BT, Vb, gam = ARq[q], KTq[q], BTq[q], Vbq[z], gamq[z]
            i = proj8(lambda c: wr[:, c, cs], xr, "xr", ["rwkv_wr"])
            P.cp("scalar", t["r"][:], pp[i], [f"pp{i}"], ["r"])
            i = proj8(lambda c: wk[:, c, cs], xk, "xk", ["rwkv_wk"])
            P.cp("scalar", t["k"][:], pp[i], [f"pp{i}"], ["k"])
            i = proj8(lambda c: wv[:, c, cs], xv, "xv", ["rwkv_wv"])
            vt4 = VTbd[:].rearrange("p u (h s) -> p u h s", h=2)
            for h2 in range(2):
                sl = slice(h2 * 64, (h2 + 1) * 64)
                P.cp("scalar", vt4[sl, :, h2, :], v3(pp[i][sl, :]), [f"pp{i}"], ["VTbd"])
            i = nxt("pp")
            P.mm(pp[i], g2[:, cs], sg[:], True, True, ["g2", "sg"], [f"pp{i}"])
            gt4 = GTbd[:].rearrange("p u (h s) -> p u h s", h=2)
            for h2 in range(2):
                sl = slice(h2 * 64, (h2 + 1) * 64)
                P.cp("scalar", gt4[sl, :, h2, :], v3(pp[i][sl, :]), [f"pp{i}"], ["GTbd"])
            yield
            j = nxt("pb")
            for u in range(4):
                P.tr(pb[j][:, u * 128:(u + 1) * 128], VTbd[:, u, :], identf[:], ["VTbd", "identf"], [f"pb{j}"])
            pv = u128(pb[j][:])
            for h2 in range(2):
                sl = slice(h2 * 64, (h2 + 1) * 64)
                P.cp("scalar", Vf[sl, :, :], pv[sl, :, h2 * 64:(h2 + 1) * 64], [f"pb{j}"], ["Vf"])
            P.cp("gpsimd", Vb[:], Vf[:], ["Vf"], [f"Vb{z}"])
            P.dma("sync", S["vst"][tg, oc].rearrange("p (u s) -> p u s", s=64), Vf[:], reads=["Vf"], writes=[("vst", tg, oc)], sem="Vf")
            j = nxt("pb")
            for u in range(4):
                P.tr(pb[j][:, u * 128:(u + 1) * 128], GTbd[:, u, :], identf[:], ["GTbd", "identf"], [f"pb{j}"])
            pv = u128(pb[j][:])
            for h2 in range(2):
                sl = slice(h2 * 64, (h2 + 1) * 64)
                P.cp("scalar", Gf[sl, :, :], pv[sl, :, h2 * 64:(h2 + 1) * 64], [f"pb{j}"], ["Gf"])
            P.dma("sync", S["gst"][tg, oc].rearrange("p (u s) -> p u s", s=64), Gf[:], reads=["Gf"], writes=[("gst", tg, oc)], sem="Gf")
            yield
            for d in range(2):
                dl = slice(d * 64, (d + 1) * 64)
                i = nxt("pp")
                P.mm(pp[i], w2s[dl, cs], lwt[dl, :], True, True, ["w2s", "lwt"], [f"pp{i}"])
                P.act(t[f"sw{d}"][:], pp[i], AF.Sigmoid, [f"pp{i}", "vec"], [f"sw{d}"], bias=vec[:, 11 + d, oc:oc + 1])
                i = nxt("pp")
                P.mm(pp[i], a2s[dl, cs], lat[dl, :], True, True, ["a2s", "lat"], [f"pp{i}"])
                P.act(t[f"ag{d}"][:], pp[i], AF.Sigmoid, [f"pp{i}", "vec"], [f"ag{d}"], bias=vec[:, 13 + d, oc:oc + 1])
            yield
            P.ts("vector", t["kq"][:], t["k"][:], vec[:, 15, oc:oc + 1], None, ALU.mult, None, ["k", "vec"], ["kq"])
            P.act(sqb[:], t["kq"][:], AF.Square, ["kq"], ["sqb"])
            i = nxt("pp")
            P.mm(pp[i], bones[:], sqb[:], True, True, ["bones", "sqb"], [f"pp{i}"])
            P.act(t["lnv"][:], pp[i], AF.Ln, [f"pp{i}"], ["lnv"], bias=1e-12)
            P.act(t["rs"][:], t["lnv"][:], AF.Exp, ["lnv"], ["rs"], scale=-0.5)
            P.tt("gpsimd", t["kkn"][:], t["kq"][:], t["rs"][:], ALU.mult, ["kq", "rs"], ["kkn"])
            for d in range(2):
                sw, ag, kd, bb = t[f"sw{d}"], t[f"ag{d}"], t[f"kd{d}"], t[f"b{d}"]
                EE = "gpsimd" if d == 0 else "vector"
                P.ts(EE, t["fac"][:], ag[:], vec[:, 16, oc:oc + 1], vec[:, 17, oc:oc + 1], ALU.mult, ALU.add, [f"ag{d}", "vec"], ["fac"])
                P.tt(EE, kd[:], t["k"][:], t["fac"][:], ALU.mult, ["k", "fac"], [f"kd{d}"])
                P.tt(EE, bb[:], t["kkn"][:], ag[:], ALU.mult, ["kkn", f"ag{d}"], [f"b{d}"])
                P.op("vector", lambda e, sw=sw: e.tensor_tensor_scan(out=t["L"][:], data0=rmask[:], data1=sw[:], initial=0.0,
                                                                      op0=ALU.mult, op1=ALU.add), [f"sw{d}", "rmask"], ["L"])
                L3 = v3(t["L"][:])
                if d == 0:
                    P.tt(EE, t["Lx"][:], t["L"][:], sw[:], ALU.subtract, ["L", f"sw{d}"], ["Lx"])
                    Li, Lin = t["L"], "L"
                else:
                    P.tt(EE, v3(t["Lx"][:]), L3[:, :, 63:64].broadcast_to([128, 4, 64]), L3, ALU.subtract, ["L"], ["Lx"])
                    P.tt(EE, t["Lb"][:], t["Lx"][:], sw[:], ALU.add, ["Lx", f"sw{d}"], ["Lb"])
                    Li, Lin = t["Lb"], "Lb"
                P.act(t["E1"][:], Li[:], AF.Exp, [Lin], ["E1"], scale=-C0)
                P.act(t["E3"][:], Li[:], AF.Exp, [Lin], ["E3"], scale=C0)
                P.act(t["E2"][:], t["Lx"][:], AF.Exp, ["Lx"], ["E2"], scale=-C0)
                ar5 = AR[d][:].rearrange("p u a (h s) -> p u a h s", h=2)
                kt4 = KT[d][:].rearrange("p u (h s) -> p u h s", h=2)
                bt4 = BT[d][:].rearrange("p u (h s) -> p u h s", h=2)
                for h2 in range(2):
                    sl = slice(h2 * 64, (h2 + 1) * 64)
                    P.stt(ar5[sl, :, 0, h2, :], v3(t["kkn"][sl, :]), -1.0, v3(t["E2"][sl, :]), ALU.mult, ALU.mult, ["kkn", "E2"], [f"AR{q}{d}"])
                    P.tt(EE, ar5[sl, :, 1, h2, :], v3(t["r"][sl, :]), v3(t["E1"][sl, :]), ALU.mult, ["r", "E1"], [f"AR{q}{d}"])
                    P.tt(EE, kt4[sl, :, h2, :], v3(kd[sl, :]), v3(t["E3"][sl, :]), ALU.mult, [f"kd{d}", "E3"], [f"KT{q}{d}"])
                    P.tt(EE, bt4[sl, :, h2, :], v3(bb[sl, :]), v3(t["E3"][sl, :]), ALU.mult, [f"b{d}", "E3"], [f"BT{q}{d}"])
                E13 = v3(t["E1"][:])
                gsrc = E13[:, :, 63] if d == 0 else E13[:, :, 0]
                P.cp("vector", gam[d][:], gsrc, ["E1"], [f"gam{z}{d}"])
                if d == 1:
                    P.cp("gpsimd", gamb_t[:, oc, :], gam[1][:], [f"gam{z}1"], ["gamb_t"])
                yield
            P.tt("gpsimd", t["ks"][:], t["kd0"][:], t["kd1"][:], ALU.add, ["kd0", "kd1"], ["ks"])
            for h2 in range(2):
                sl = slice(h2 * 64, (h2 + 1) * 64)
                P.stt(RK[sl, :, h2, :], v3(t["r"][sl, :]), vec[sl, 18, oc:oc + 1], v3(t["ks"][sl, :]), ALU.mult, ALU.mult, ["r", "ks", "vec"], ["RK"])
            i = nxt("pp")
            for u in range(4):
                P.mm(pp[i][:, u:u + 1], RK[:, u, :, :].rearrange("p h s -> p (h s)"), onesb[:, 0:1], True, True, ["RK", "onesb"], [f"pp{i}"])
            P.cp("scalar", bon_t[:, oc, :], pp[i][:, 0:4], [f"pp{i}"], ["bon_t"])
            if oc == 7:
                P.dma("sync", S["gamb"][tg], gamb_t[:].rearrange("p a b -> p (a b)"), reads=["gamb_t"], writes=[("gamb", tg)], sem="gamb_t")
                P.dma("sync", S["bon"][tg], bon_t[:].rearrange("p a b -> p (a b)"), reads=["bon_t"], writes=[("bon", tg)], sem="bon_t")
            yield

        def chain(ti, oc, q, d, z):
            AR, KT, BT, Vb = ARq[q][d], KTq[q][d], BTq[q][d], Vbq[z]
            ARn, KTn, BTn, Vbn = f"AR{q}{d}", f"KT{q}{d}", f"BT{q}{d}", f"Vb{z}"
            iv, fn = inv[d], fin[q][d]
            IR = lambda nm: f"i{d}_{nm}"
            FR = lambda nm: f"f{q}{d}_{nm}"
            mS, mC = (0, 2) if d == 0 else (2, 0)
            mSI = masks[:, mS:mS + 2, :].rearrange("p a b -> p (a b)").unsqueeze(1).broadcast_to([128, 4, 256])
            mCb = masks[:, mC, :].unsqueeze(1).broadcast_to([128, 4, 128])
            idb = identb[:].unsqueeze(1).broadcast_to([128, 4, 128])
            for src, srcn, dst, dstn in ((AR[:, :, 0, :], ARn, iv["Atok"], IR("Atok")), (BT[:], BTn, iv["Btok"], IR("Btok")),
                                         (KT[:], KTn, fn["Ktok"], FR("Ktok"))):
                j = nxt("pb")
                pbt = pb[j][:].bitcast(BF16)
                for u in range(4):
                    P.tr(pbt[:, u * 128:(u + 1) * 128], src[:, u, :], identb[:], [srcn, "identb"], [f"pb{j}"])
                P.cp("scalar", dst[:].rearrange("p u x -> p (u x)"), pbt[:, 0:512], [f"pb{j}"], [dstn])
            mSb = masks[:, mS, :].unsqueeze(1).broadcast_to([128, 4, 128])
            mIb = masks[:, mS + 1, :].unsqueeze(1).broadcast_to([128, 4, 128])

            def two_bank(mm_fn):
                j0, j1 = nxt("pb"), nxt("pb")
                for u in range(4):
                    mm_fn(u, pb[j0][:, u * 128:(u + 1) * 128], f"pb{j0}", pb[j1][:, u * 128:(u + 1) * 128], f"pb{j1}")
                return j0, j1

            for lhs, lhsn, dst, dstn in ((BT, BTn, iv["MQ"], IR("MQ")), (KT, KTn, fn["NP"], FR("NP"))):
                def mm_ab(u, o0, n0, o1, n1, lhs=lhs, lhsn=lhsn):
                    P.mm(o0, lhs[:, u, :], AR[:, u, 0, :], True, True, [lhsn, ARn], [n0])
                    P.mm(o1, lhs[:, u, :], AR[:, u, 1, :], True, True, [lhsn, ARn], [n1])
                j0, j1 = two_bank(mm_ab)
                P.tt("vector", dst[:, :, 0:128], u128(pb[j0][:]), mSb, ALU.mult, [f"pb{j0}", "masks"], [dstn])
                P.tt("vector", dst[:, :, 128:256], u128(pb[j1][:]), mIb, ALU.mult, [f"pb{j1}", "masks"], [dstn])
            j = nxt("pb")
            for u in range(4):
                P.mm(pb[j][:, u * 128:(u + 1) * 128], AR[:, u, 0, :], BT[:, u, :], True, True, [ARn, BTn], [f"pb{j}"])
            cur, curn, nx, nxn = iv["MWa"], IR("MWa"), iv["MWb"], IR("MWb")
            P.tt("vector", cur[:, :, 0, :], u128(pb[j][:]), mCb, ALU.mult, [f"pb{j}", "masks"], [curn])
            yield
            j = nxt("pb")
            for u in range(4):
                P.mm(pb[j][:, u * 128:(u + 1) * 128], iv["MQ"][:, u, 0:128], cur[:, u, 0, :], True, True, [IR("MQ"), curn], [f"pb{j}"])
            P.cp("scalar", nx[:, :, 0, :], u128(pb[j][:]), [f"pb{j}"], [nxn])
            P.tt("gpsimd", nx[:, :, 1, :], cur[:, :, 0, :], idb, ALU.add, [curn, "identb"], [nxn])
            j = nxt("pb")
            for u in range(4):
                P.mm(pb[j][:, u * 128:(u + 1) * 128], cur[:, u, 0, :], iv["MQ"][:, u, 0:128], True, True, [IR("MQ"), curn], [f"pb{j}"])
            curT, curTn, nxT, nxTn = iv["MTa"], IR("MTa"), iv["MTb"], IR("MTb")
            P.cp("scalar", curT[:], u128(pb[j][:]), [f"pb{j}"], [curTn])
            cur, curn, nx, nxn = nx, nxn, cur, curn
            yield
            for lev in range(1, 5):
                def mm_lev(u, o0, n0, o1, n1, cur=cur, curn=curn, curT=curT, curTn=curTn):
                    P.mm(o0, curT[:, u, :], cur[:, u, 0, :], True, True, [curTn, curn], [n0])
                    P.mm(o1, curT[:, u, :], cur[:, u, 1, :], True, True, [curTn, curn], [n1])
                j0, j1 = two_bank(mm_lev)
                P.cp("scalar", nx[:, :, 0, :], u128(pb[j0][:]), [f"pb{j0}"], [nxn])
                P.tt("vector", nx[:, :, 1, :], u128(pb[j1][:]), cur[:, :, 1, :], ALU.add, [f"pb{j1}", curn], [nxn])
                j = nxt("pb")
                for u in range(4):
                    P.mm(pb[j][:, u * 128:(u + 1) * 128], cur[:, u, 0, :], curT[:, u, :], True, True, [curn, curTn], [f"pb{j}"])
                P.cp("scalar", nxT[:], u128(pb[j][:]), [f"pb{j}"], [nxTn])
                cur, curn, nx, nxn = nx, nxn, cur, curn
                curT, curTn, nxT, nxTn = nxT, nxTn, curT, curTn
                yield
            j = nxt("pb")
            for u in range(4):
                P.mm(pb[j][:, u * 128:(u + 1) * 128], curT[:, u, :], cur[:, u, 1, :], True, True, [curTn, curn], [f"pb{j}"])
            P.tt("vector", nx[:, :, 1, :], u128(pb[j][:]), cur[:, :, 1, :], ALU.add, [f"pb{j}", curn], [nxn])
            W6, W6n = nx, nxn
            j = nxt("pb")
            for u in range(4):
                P.mm(pb[j][:, u * 64:(u + 1) * 64], fn["NP"][:, u, 0:128], Vb[:, u, :], True, True, [FR("NP"), Vbn], [f"pb{j}"])
            P.cp("scalar", fn["NVb"][:].rearrange("p u x -> p (u x)"), pb[j][:, 0:256], [f"pb{j}"], [FR("NVb")])
            yield

            def mm_d(u, o0, n0, o1, n1):
                P.mm(o0, W6[:, u, 1, :], iv["MQ"][:, u, 128:256], True, True, [W6n, IR("MQ")], [n0])
                P.mm(o1, W6[:, u, 1, :], iv["Btok"][:, u, :], True, True, [W6n, IR("Btok")], [n1])
            j0, j1 = two_bank(mm_d)
            P.cp("scalar", fn["XW"][:, :, 0:128], u128(pb[j0][:]), [f"pb{j0}"], [FR("XW")])
            P.cp("vector", fn["XW"][:, :, 128:256], u128(pb[j1][:]), [f"pb{j1}"], [FR("XW")])
            yield

            def mm_f(u, o0, n0, o1, n1):
                P.mm(o0, iv["Atok"][:, u, :], fn["XW"][:, u, 0:128], True, True, [IR("Atok"), FR("XW")], [n0])
                P.mm(o1, iv["Atok"][:, u, :], fn["XW"][:, u, 128:256], True, True, [IR("Atok"), FR("XW")], [n1])
            j0, j1 = two_bank(mm_f)
            P.tt("vector", fn["GY"][:], u128(pb[j0][:]), AR[:, :, 1, :], ALU.add, [f"pb{j0}", ARn], [FR("GY")])
            P.tt("vector", fn["GS"][:], u128(pb[j1][:]), idb, ALU.add, [f"pb{j1}", "identb"], [FR("GS")])
            yield

        def finish(ti, oc, q, z):
            isctx, idx = RW_ORDER1[ti]
            tg = 16 if isctx else idx
            sf, sb_ = fin[q]
            F0 = lambda nm: f"f{q}0_{nm}"
            F1 = lambda nm: f"f{q}1_{nm}"
            Vb, Vbn, gam = Vbq[z], f"Vb{z}", gamq[z]
            SFR = ("Sf", oc)
            for u in range(4):
                yo = pf[:, u * 64:(u + 1) * 64]
                P.mm(yo, sf["NP"][:, u, 128:256], Vb[:, u, :], True, False, [F0("NP"), Vbn], ["pf"])
                P.mm(yo, sf["XW"][:, u, 0:128], sf["NVb"][:, u, :], False, False, [F0("XW"), F0("NVb")], ["pf"])
                P.mm(yo, sb_["NP"][:, u, 128:256], Vb[:, u, :], False, False, [F1("NP"), Vbn], ["pf"])
                P.mm(yo, sb_["XW"][:, u, 0:128], sb_["NVb"][:, u, :], False, False, [F1("XW"), F1("NVb")], ["pf"])
                P.mm(yo, sf["GY"][:, u, :], Sf[:, oc, :], False, True, [F0("GY"), SFR], ["pf"])
                so = pf[:, 256:320]
                P.mm(so, sf["Ktok"][:, u, :], Vb[:, u, :], True, False, [F0("Ktok"), Vbn], ["pf"])
                P.mm(so, sf["XW"][:, u, 128:256], sf["NVb"][:, u, :], False, False, [F0("XW"), F0("NVb")], ["pf"])
                P.mm(so, sf["GS"][:, u, :], Sf[:, oc, :], False, True, [F0("GS"), SFR], ["pf"])
                P.ts("vector", Sf[:, oc, :], so, gam[0][:, u:u + 1], None, ALU.mult, None, ["pf", f"gam{z}0"], [SFR])
                yield
            P.cp("vector", YPs[:].rearrange("p u x -> p (u x)"), pf[:, 0:256], ["pf"], ["YPs"])
            P.dma("sync", S["yp"][tg, oc], YPs[:].rearrange("p u x -> p (u x)"), reads=["YPs"], writes=[("yp", tg, oc)], sem="YPs")
            j = nxt("pb")
            for u in range(4):
                so = pb[j][:, u * 64:(u + 1) * 64]
                P.mm(so, sb_["Ktok"][:, u, :], Vb[:, u, :], True, False, [F1("Ktok"), Vbn], [f"pb{j}"])
                P.mm(so, sb_["XW"][:, u, 128:256], sb_["NVb"][:, u, :], False, True, [F1("XW"), F1("NVb")], [f"pb{j}"])
            P.cp("scalar", SAs[:].rearrange("p u x -> p (u x)"), pb[j][:, 0:256], [f"pb{j}"], ["SAs"])
            P.dma("sync", S["sadd"][tg, oc], SAs[:].rearrange("p u x -> p (u x)"), reads=["SAs"], writes=[("sadd", tg, oc)], sem="SAs")
            P.dma("sync", S["gyb"][tg, oc], sb_["GY"][:].rearrange("p u x -> p (u x)"), reads=[F1("GY")], writes=[("gyb", tg, oc)], sem=F1("GY"))
            P.dma("sync", S["gsb"][tg, oc], sb_["GS"][:].rearrange("p u x -> p (u x)"), reads=[F1("GS")], writes=[("gsb", tg, oc)], sem=F1("GS"))
            yield

        NT = len(RW_ORDER1)
        NJ = NT * 8
        done = {"prep": set(), "c0": set(), "c1": set(), "fin": set(), "tprep": set()}

        def stream_P():
            for ti in range(NT):
                yield ("tprep", ti, lambda ti=ti: (ti == 0 or ("prep", (ti - 1) * 8 + 7) in donef), lambda ti=ti: tprep(ti))
                for oc in range(8):
                    k = ti * 8 + oc
                    yield ("prep", k, lambda k=k: ((k < 2 or (("c0", k - 2) in donef and ("c1", k - 2) in donef)) and (k < 3 or ("fin", k - 3) in donef)),
                           lambda ti=ti, oc=oc, k=k: prep(ti, oc, k % 2, k % 3))

        def stream_C(d):
            for k in range(NJ):
                ti, oc = divmod(k, 8)
                yield (f"c{d}", k, lambda k=k: (("prep", k) in donef and (k < 2 or ("fin", k - 2) in donef)),
                       lambda ti=ti, oc=oc, k=k: chain(ti, oc, k % 2, d, k % 3))

        def stream_F():
            for k in range(NJ):
                ti, oc = divmod(k, 8)
                yield ("fin", k, lambda k=k: (("c0", k) in donef and ("c1", k) in donef),
                       lambda ti=ti, oc=oc, k=k: finish(ti, oc, k % 2, k % 3))

        donef = set()
        load_hh(0)
        streams = [stream_C(0), stream_C(1), stream_F(), stream_P()]
        cur = [None] * 4
        pend = [None] * 4
        alive = [True] * 4
        while any(alive):
            progressed = False
            for si in range(4):
                if not alive[si]:
                    continue
                if cur[si] is None:
                    if pend[si] is None:
                        try:
                            pend[si] = next(streams[si])
                        except StopIteration:
                            alive[si] = False
                            continue
                    kind, k, ready, mk = pend[si]
                    if not ready():
                        continue
                    cur[si] = (kind, k, mk())
                    pend[si] = None
                kind, k, gen = cur[si]
                try:
                    next(gen)
                    progressed = True
                except StopIteration:
                    donef.add((kind, k))
                    cur[si] = None
                    progressed = True
            assert progressed or not any(alive), "scheduler stuck"


def stage_rwkv2(P, io, G, S, src, xa):
    vec, identb = G["vec"], G["identb"]
    GN_EPS = 64e-5
    with P.phase("rwkv2"):
        wo = P.sb([64, 16, 1024], BF16)
        P.dma("gpsimd", wo[:], io["rwkv_wo"].rearrange("(h v) f -> v h f", v=64), writes=["wo"], sem="wo")
        lnw = P.sb([128, 8, 64], F32)
        lnb = P.sb([128, 8, 64], F32)
        P.dma("sync", lnw[:], io["lnw_st"], writes=["lnw"], sem="lnw")
        P.dma("sync", lnb[:], io["lnb_st"], writes=["lnb"], sem="lnb")
        big = {}
        for nm in ("yp", "sadd", "vst", "gst"):
            big[nm] = [P.sb([128, 8, 256], F32, f"l_{nm}{b}") for b in range(2)]
        for nm in ("gyb", "gsb"):
            big[nm] = [P.sb([128, 8, 512], BF16, f"l_{nm}{b}") for b in range(2)]
        gamb = [P.sb([128, 8, 4], F32) for _ in range(2)]
        bon = [P.sb([128, 8, 4], F32) for _ in range(2)]
        xt = [P.sb([128, 8, 256], F32) for _ in range(2)]
        Sb = P.sb([128, 8, 64], BF16)
        ysb2 = [P.sb([128, 8, 64], F32) for _ in range(2)]
        ysq2 = [P.sb([128, 8, 64], F32) for _ in range(2)]
        tmpS = P.sb([128, 8, 64], F32)
        yn2 = [P.sb([128, 8, 64], F32) for _ in range(2)]
        bv2 = [P.sb([128, 8, 64], F32) for _ in range(2)]
        ob2 = [P.sb([128, 8, 64], BF16) for _ in range(2)]
        st2 = [{nm: P.sb([128, 8], F32, f"g{k_}_" + nm) for nm in ("s1", "s2", "mean", "msq", "var", "lnv", "rstd")} for k_ in range(2)]
        OT = P.sb([64, 16, 256], BF16)
        py = [P.ps([128, 512], F32) for _ in range(2)]
        pS = P.ps([128, 512], F32)
        ptr = P.ps([128, 1024], F32)
        pw = [P.ps([128, 512], F32) for _ in range(2)]
        P.memset("gpsimd", Sb[:], 0.0, ["Sb"])

        def load(k):
            isctx, idx = RW_ORDER2[k]
            tg = 16 if isctx else idx
            b = k % 2
            for nm in ("yp", "sadd", "vst", "gst", "gyb", "gsb"):
                P.dma("sync", big[nm][b][:], S[nm][tg].rearrange("o p x -> p o x"), writes=[f"{nm}{b}"], sem=f"{nm}{b}")
            P.dma("sync", gamb[b][:].rearrange("p a b -> p (a b)"), S["gamb"][tg], writes=[f"gamb{b}"], sem=f"gamb{b}")
            P.dma("sync", bon[b][:].rearrange("p a b -> p (a b)"), S["bon"][tg], writes=[f"bon{b}"], sem=f"bon{b}")
            c0 = T if isctx else idx * 256
            P.dma("sync", xt[b][:], fm(src[:, c0:c0 + 256]), writes=[f"xt{b}"], sem=f"xt{b}")

        load(0)
        for k, (isctx, idx) in enumerate(RW_ORDER2):
            b = k % 2
            if k + 1 < len(RW_ORDER2):
                load(k + 1)
            c0 = T if isctx else idx * 256
            _, _, gates = mod_scalars(G, 0, 0, isctx)
            bc = lambda ap: ap.unsqueeze(2).broadcast_to([128, 8, 64])
            def chain_part(u):
                us = slice(u * 64, (u + 1) * 64)
                q_ = u % 2
                for oc in range(8):
                    P.mm(py[q_][:, oc * 64:(oc + 1) * 64], big["gyb"][b][:, oc, u * 128:(u + 1) * 128], Sb[:, oc, :], True, True, [f"gyb{b}", "Sb"], [f"py{q_}"])
                for oc in range(8):
                    P.mm(pS[:, oc * 64:(oc + 1) * 64], big["gsb"][b][:, oc, u * 128:(u + 1) * 128], Sb[:, oc, :], True, True, [f"gsb{b}", "Sb"], ["pS"])
                pS3 = pS[:].rearrange("p (o v) -> p o v", v=64)
                P.tt("vector", tmpS[:], pS3, big["sadd"][b][:, :, us], ALU.add, ["pS", f"sadd{b}"], ["tmpS"])
                P.tt("vector", Sb[:], tmpS[:], bc(gamb[b][:, :, u]), ALU.mult, ["tmpS", f"gamb{b}"], ["Sb"])

            def read_part(u):
                us = slice(u * 64, (u + 1) * 64)
                q_ = u % 2
                ysb, ysq, yn, bv, ob, st = ysb2[q_], ysq2[q_], yn2[q_], bv2[q_], ob2[q_], st2[q_]
                N = lambda nm: f"{nm}{q_}"
                py3 = py[q_][:].rearrange("p (o v) -> p o v", v=64)
                P.tt("vector", ysb[:], py3, big["yp"][b][:, :, us], ALU.add, [f"py{q_}", f"yp{b}"], [N("ysb")])
                P.tt("gpsimd", bv[:], big["vst"][b][:, :, us], bc(bon[b][:, :, u]), ALU.mult, [f"vst{b}", f"bon{b}"], [N("bv")])
                yield
                P.op("vector", lambda e: e.tensor_reduce(out=st["s1"][:], in_=ysb[:], axis=AX.X, op=ALU.add), [N("ysb")], [N("s1")])
                P.tt("gpsimd", ysq[:], ysb[:], ysb[:], ALU.mult, [N("ysb")], [N("ysq")])
                yield
                P.op("vector", lambda e: e.tensor_reduce(out=st["s2"][:], in_=ysq[:], axis=AX.X, op=ALU.add), [N("ysq")], [N("s2")])
                P.ts("vector", st["mean"][:], st["s1"][:], 1.0 / 64, None, ALU.mult, None, [N("s1")], [N("mean")])
                P.tt("vector", st["msq"][:], st["mean"][:], st["mean"][:], ALU.mult, [N("mean")], [N("msq")])
                P.stt(st["var"][:], st["s2"][:], 1.0 / 64, st["msq"][:], ALU.mult, ALU.subtract, [N("s2"), N("msq")], [N("var")])
                yield
                P.act(st["lnv"][:], st["var"][:], AF.Ln, [N("var")], [N("lnv")], bias=GN_EPS)
                P.act(st["rstd"][:], st["lnv"][:], AF.Exp, [N("lnv")], [N("rstd")], scale=-0.5)
                P.tt("gpsimd", yn[:], ysb[:], bc(st["mean"][:]), ALU.subtract, [N("ysb"), N("mean")], [N("yn")])
                yield
                P.tt("vector", yn[:], yn[:], bc(st["rstd"][:]), ALU.mult, [N("yn"), N("rstd")], [N("yn")])
                yield
                P.tt("gpsimd", yn[:], yn[:], lnw[:], ALU.mult, [N("yn"), "lnw"], [N("yn")])
                yield
                P.tt("vector", yn[:], yn[:], lnb[:], ALU.add, [N("yn"), "lnb"], [N("yn")])
                yield
                P.tt("gpsimd", yn[:], yn[:], bv[:], ALU.add, [N("yn"), N("bv")], [N("yn")])
                yield
                P.tt("vector", ob[:], yn[:], big["gst"][b][:, :, us], ALU.mult, [N("yn"), f"gst{b}"], [N("ob")])
                yield
                ptb = ptr[:].bitcast(BF16)
                for oc in range(8):
                    P.tr(ptb[0:64, oc * 128:(oc + 1) * 128], ob[:, oc, :], identb[:], [N("ob"), "identb"], ["ptr"])
                P.cp("scalar", OT[:, :, us], ptb[0:64, 0:1024].rearrange("p (h t) -> p h t", t=64), ["ptr"], ["OT"])
                yield

            def chain_all():
                for u in range(3, -1, -1):
                    chain_part(u)
                    yield

            jobs = [read_part(u) for u in range(3, -1, -1)]
            cgen = chain_all()
            next(cgen)
            active = []
            started = 0
            while jobs or active:
                while jobs and len(active) < 2:
                    if started >= 1:
                        try:
                            next(cgen)
                        except StopIteration:
                            pass
                    active.append(jobs.pop(0))
                    started += 1
                for gen in list(active):
                    try:
                        next(gen)
                    except StopIteration:
                        active.remove(gen)
            for oc in range(8):
                j = oc % 2
                for h in range(16):
                    P.mm(pw[j][:, 0:256], wo[:, h, oc * 128:(oc + 1) * 128], OT[:, h, :], h == 0, h == 15, ["wo", "OT"], [f"pw{j}"])
                P.stt(xt[b][:, oc, :], pw[j][:, 0:256], gates[oc], xt[b][:, oc, :], ALU.mult, ALU.add, [f"pw{j}", f"xt{b}", "modv"], [f"xt{b}"])
            P.dma("sync", fm(xa[:, c0:c0 + 256]), xt[b][:], reads=[f"xt{b}"], writes=[("xa", k)], sem=f"xt{b}")


def stage_qkv(P, io, G, hb, qtd, Kz, VA):
    vec, bones, perm = G["vec"], G["bones"], G["perm"]
    with P.phase("qkv"):
        wq = P.sb([128, 8, 1024], BF16)
        wkd = P.sb([128, 8, 512], BF16)
        wv = P.sb([128, 8, 256], BF16)
        P.dma("gpsimd", wq[:], fm(io["attn_wq"]), writes=["wq"], sem="wq")
        P.dma("gpsimd", wkd[:], fm(io["attn_wkd"]), writes=["wkd"], sem="wkd")
        P.dma("gpsimd", wv[:], fm(io["attn_wv"]), writes=["wv"], sem="wv")
        ht = [P.sb([128, 8, 512], BF16) for _ in range(2)]
        cs = [P.sb([128, 512], F32) for _ in range(2)]
        sn = [P.sb([128, 512], F32) for _ in range(2)]
        NB = 2
        qf = [P.sb([128, 512], F32) for _ in range(NB)]
        sqb = [P.sb([128, 512], BF16) for _ in range(NB)]
        lnv = [P.sb([128, 512], F32) for _ in range(NB)]
        rstd = [P.sb([128, 512], F32) for _ in range(NB)]
        qh = [P.sb([128, 512], F32) for _ in range(NB)]
        qhb = [P.sb([128, 512], BF16) for _ in range(NB)]
        t1 = [P.sb([128, 512], F32) for _ in range(NB)]
        t2 = [P.sb([128, 512], F32) for _ in range(NB)]
        qst = [P.sb([128, 8, 512], BF16) for _ in range(2)]
        pp = [P.ps([128, 512], F32) for _ in range(6)]
        cnt = [0, 0]

        def nxt():
            cnt[0] += 1
            return cnt[0] % 6

        P.memset("gpsimd", VA[:], 0.0, ["VA0"])
        P.memset("gpsimd", VA[:].rearrange("p k (j x) -> p k j x", x=65)[:, :, 0:5, 64:65], 1.0, ["VA0"])
        P.memset("gpsimd", Kz[0][64:128, :, :], 0.0, ["Kz0z"])
        P.memset("gpsimd", Kz[1][0:64, :, :], 0.0, ["Kz1z"])
        tiles = ALL_TILES

        def load(i):
            c0, tw, isctx = tiles[i]
            b = i % 2
            P.dma("sync", ht[b][:, :, :tw], fm(hb[:, c0:c0 + tw]), writes=[f"ht{b}"], sem=f"ht{b}")
            if not isctx:
                P.dma("sync", cs[b][:, :tw], io["cosT"][:, c0:c0 + tw], writes=[f"cs{b}"], sem=f"cs{b}")
                P.dma("sync", sn[b][:, :tw], io["sinT"][:, c0:c0 + tw], writes=[f"sn{b}"], sem=f"sn{b}")

        def normrope(wcols, nscal, dsts, b, tw, isctx, wname, dres="dstqk"):
            cnt[1] += 1
            n = cnt[1] % NB
            i = nxt()
            for c in range(8):
                P.mm(pp[i][:, :tw], wcols(c), ht[b][:, c, :tw], c == 0, c == 7, [wname, f"ht{b}"], [f"pp{i}"])
            P.cp("scalar", qf[n][:, :tw], pp[i][:, :tw], [f"pp{i}"], [f"qf{n}"])
            P.act(sqb[n][:, :tw], qf[n][:, :tw], AF.Square, [f"qf{n}"], [f"sqb{n}"])
            yield
            i = nxt()
            P.mm(pp[i][:, :tw], bones[:], sqb[n][:, :tw], True, True, ["bones", f"sqb{n}"], [f"pp{i}"])
            P.act(lnv[n][:, :tw], pp[i][:, :tw], AF.Ln, [f"pp{i}"], [f"lnv{n}"], bias=1e-6, scale=1.0 / 64)
            P.act(rstd[n][:, :tw], lnv[n][:, :tw], AF.Exp, [f"lnv{n}"], [f"rstd{n}"], scale=-0.5)
            yield
            P.stt(qh[n][:, :tw], qf[n][:, :tw], nscal, rstd[n][:, :tw], ALU.mult, ALU.mult, [f"qf{n}", f"rstd{n}", "vec"], [f"qh{n}"])
            if isctx:
                for dst, sl in dsts:
                    P.cp("gpsimd", dst, qh[n][sl, :tw], [f"qh{n}"], [dres])
                return
            P.cp("gpsimd", qhb[n][:, :tw], qh[n][:, :tw], [f"qh{n}"], [f"qhb{n}"])
            yield
            i = nxt()
            P.mm(pp[i][:, :tw], perm[:], qhb[n][:, :tw], True, True, ["perm", f"qhb{n}"], [f"pp{i}"])
            P.tt("gpsimd", t1[n][:, :tw], qh[n][:, :tw], cs[b][:, :tw], ALU.mult, [f"qh{n}", f"cs{b}"], [f"t1{n}"])
            P.tt("vector", t2[n][:, :tw], pp[i][:, :tw], sn[b][:, :tw], ALU.mult, [f"pp{i}", f"sn{b}"], [f"t2{n}"])
            yield
            for dst, sl in dsts:
                P.tt("gpsimd", dst, t1[n][sl, :tw], t2[n][sl, :tw], ALU.add, [f"t1{n}", f"t2{n}"], [dres])

        ALLP = slice(0, 128)
        load(0)
        for i, (c0, tw, isctx) in enumerate(tiles):
            b = i % 2
            if i + 1 < len(tiles):
                load(i + 1)
            jobs = []
            if not isctx:
                for oc in range(8):
                    jobs.append(normrope(lambda c, oc=oc: wq[:, c, oc * 128:(oc + 1) * 128], vec[:, 19, oc:oc + 1], [(qst[b][:, oc, :tw], ALLP)], b, tw, False, "wq",
                                         dres=(f"qst{b}", oc)))
            for g in range(4):
                jobs.append(normrope(lambda c, g=g: wkd[:, c, g * 128:(g + 1) * 128], vec[:, 20, 0:1],
                                     [(Kz[0][0:64, g, c0:c0 + tw], slice(0, 64)), (Kz[1][64:128, g, c0:c0 + tw], slice(64, 128))], b, tw, isctx, "wkd"))

            def vjob():
                for sub in range(tw // 128):
                    kt = c0 // 128 + sub
                    j = nxt()
                    for c in range(8):
                        P.mm(pp[j][:, 0:256], ht[b][:, c, sub * 128:(sub + 1) * 128], wv[:, c, :], c == 0, c == 7, ["wv", f"ht{b}"], [f"pp{j}"])
                    P.cp("scalar", VA[:, kt, 65:325].rearrange("p (g x) -> p g x", x=65)[:, :, 0:64],
                         pp[j][:, 0:256].rearrange("p (g d) -> p g d", d=64), [f"pp{j}", "VA0"], [("VA", kt)])
                    yield

            jobs.append(vjob())
            active = []
            while jobs or active:
                while jobs and len(active) < 2:
                    active.append(jobs.pop(0))
                for gen in list(active):
                    try:
                        next(gen)
                    except StopIteration:
                        active.remove(gen)
            if not isctx:
                P.dma("sync", fm(qtd[:, c0:c0 + tw]), qst[b][:, :, :tw], reads=[(f"qst{b}", oc) for oc in range(8)], writes=[("qtd", i)], sem=f"qst{b}")


def stage_attn(P, io, G, qtd, Kz, VA, xa):
    with P.phase("attn"):
        wo = P.sb([128, 8, 1024], BF16)
        P.dma("gpsimd", wo[:], fm(io["attn_wo"]), writes=["wo"], sem="wo")
        sel = P.sb([128, 2, 128], F32)
        P.dma("sync", sel[:], io["c_sel"], writes=["sel"], sem="sel")
        PT = [P.sb([128, 1024], BF16) for _ in range(3)]
        osb = [P.sb([128, 512], F32) for _ in range(2)]
        rb = [P.sb([128, 512], F32) for _ in range(2)]
        xt = P.sb([128, 8, 512], F32)
        QB = [P.sb([128, 8, 512], BF16) for _ in range(2)]
        psS = [P.ps([128, 1024], F32) for _ in range(2)]
        psO = [P.ps([128, 512], F32) for _ in range(2)]
        psB = P.ps([128, 512], F32)
        pX = [P.ps([128, 512], F32) for _ in range(1)]
        _, _, gates = mod_scalars(G, 1, 0, False)
        for k in range(2):
            P.memset("gpsimd", osb[k][:], 0.0, [f"osb{k}"])
        def loadq(qb):
            P.dma("sync", QB[qb % 2][:], fm(qtd[:, qb * 512:(qb + 1) * 512]), writes=[("QT", h, qb) for h in range(16)], sem=f"QB{qb % 2}")

        loadq(0)
        for qb in range(8):
            qsl = slice(qb * 512, (qb + 1) * 512)
            QT = QB[qb % 2]
            if qb + 1 < 8:
                loadq(qb + 1)
            P.dma("sync", xt[:], fm(xa[:, qsl]), writes=["xt"], sem="xt")
            steps = [(h, kp) for h in range(16) for kp in range(17)]

            def S(i):
                h, kp = steps[i]
                g, oc, h2 = h // 4, h // 2, h % 2
                for e_ in range(2):
                    kt = 2 * kp + e_
                    P.mm(psS[i % 2][:, e_ * 512:(e_ + 1) * 512], Kz[h2][:, g, kt * 128:(kt + 1) * 128], QT[:, oc, :], True, True,
                         ["Kz", ("QT", 2 * oc, qb), ("QT", 2 * oc + 1, qb)], [f"psS{i % 2}"])

            def epi_a(h):
                o = h % 2
                P.cp("vector", osb[o][:], psO[o][:], [f"psO{o}"], [f"osb{o}"])

            def epi_b(h):
                oc, h2, o = h // 2, h % 2, h % 2
                hs = slice(h2 * 64, h2 * 64 + 64)
                P.mm(psB[:, :], sel[:, h2, :], osb[o][:], True, True, ["sel", f"osb{o}"], ["psB"])
                P.act(rb[o][hs, :], psB[hs, :], AF.Ln, ["psB"], [f"rb{o}"])
                P.act(rb[o][hs, :], rb[o][hs, :], AF.Exp, [f"rb{o}"], [f"rb{o}"], scale=-1.0)
                P.tt("gpsimd", QT[hs, oc, :], osb[o][hs, :], rb[o][hs, :], ALU.mult, [f"osb{o}", f"rb{o}"], [("QT", h, qb)])

            S(0)
            pend = {}
            for i, (h, kp) in enumerate(steps):
                g, h2, o = h // 4, h % 2, h % 2
                if i + 1 < len(steps):
                    S(i + 1)
                p_ = i % 3
                P.act(PT[p_][:], psS[i % 2][:, :], AF.Exp, [f"psS{i % 2}"], [f"PT{p_}"], scale=0.125)
                v0 = 65 + 65 * g if h2 == 0 else 1 + 65 * g
                for e_ in range(2):
                    kt = 2 * kp + e_
                    P.mm(psO[o][:, :], VA[:, kt, v0:v0 + 128], PT[p_][:, e_ * 512:(e_ + 1) * 512], kt == 0, kt == 33, [f"PT{p_}", "VA"], [f"psO{o}"])
                if kp == 16:
                    epi_a(h)
                    pend[i + 3] = h
                if i in pend:
                    epi_b(pend.pop(i))
            for k in sorted(pend):
                epi_b(pend[k])
            for oc in range(8):
                j = 0
                for c in range(8):
                    P.mm(pX[j][:, :], wo[:, c, oc * 128:(oc + 1) * 128], QT[:, c, :], c == 0, c == 7,
                         ["wo", ("QT", 2 * c, qb), ("QT", 2 * c + 1, qb)], [f"pX{j}"])
                P.stt(xt[:, oc, :], pX[j][:, :], gates[oc], xt[:, oc, :], ALU.mult, ALU.add, [f"pX{j}", "xt", "modv"], ["xt"])
            P.dma("sync", fm(xa[:, qsl]), xt[:], reads=["xt"], writes=[("xa", qb)], sem="xt")


IN_SHAPES = {
    "xin": [D, TT], "cvec": [128, 8, 2], "w_mod": [2, D, 6 * D], "b_mod": [2, 6 * D], "vecs": [128, NV, 8],
    "mlp_w1": [2, D, 4 * D], "mlp_w2": [2, 4 * D, D],
    "rwkv_wr": [D, D], "rwkv_wk": [D, D], "rwkv_wv": [D, D], "rwkv_wo": [D, D],
    "rwkv_w1": [2, D, 64], "rwkv_w2": [2, 64, D], "rwkv_a1": [2, D, 64], "rwkv_a2": [2, 64, D],
    "rwkv_g1": [D, 128], "rwkv_g2": [128, D], "lnw_st": [128, 8, 64], "lnb_st": [128, 8, 64],
    "attn_wq": [D, D], "attn_wkd": [D, 512], "attn_wv": [D, 256], "attn_wo": [D, D],
    "cosT": [128, T], "sinT": [128, T],
    "c_ident": [128, 128], "c_ones": [128, 128], "c_bones": [128, 128], "c_masks": [128, 4, 128],
    "c_perm": [128, 128], "c_rmask": [128, 256], "c_sel": [128, 2, 128],
}


class IO(dict):
    def __init__(self, nc):
        super().__init__()
        self.nc = nc
        self.used = []

    def __missing__(self, k):
        ap = self.nc.dram_tensor(k, IN_SHAPES[k], F32, kind="ExternalInput").ap()
        self[k] = ap
        self.used.append(k)
        return ap

    def scratch(self, name, shape, dtype):
        return self.nc.dram_tensor(name, list(shape), dtype, kind="Internal").ap()

    def output(self, name, shape, dtype=F32):
        return self.nc.dram_tensor(name, list(shape), dtype, kind="ExternalOutput").ap()


def build(stages="all", dbg=None):
    nc = bass.Bass("TRN2", target_bir_lowering=False)
    io = IO(nc)
    P = Prog(nc)
    G = {}
    outs = {}
    stage_init(P, io, G)
    xa = io.scratch("xa", [D, TT], F32)
    hb = io.scratch("hb", [D, TT], BF16)
    if stages == "t_mlp":
        outs["dbg_h"] = io.output("dbg_h", [D, TT], BF16)
        stage_norm(P, io, G, "n_t", io["xin"], ALL_TILES,
                   lambda ic: mod_scalars(G, 0, 1, ic)[0], lambda ic: mod_scalars(G, 0, 1, ic)[1],
                   lambda c0, tw, ic: fm(hb[:, c0:c0 + tw]), BF16)
        with P.phase("copy"):
            P.dma("sync", xa, io["xin"], writes=["xa"], sem="cpa")
            P.dma("sync", outs["dbg_h"], hb, writes=["o"], sem="cpb")
        stage_mlp(P, io, G, 0, ALL_TILES, xa, hb)
        outs["y"] = io.output("y", [D, TT])
        fin = [G["vec"][:, 4, c:c + 1] for c in range(8)]
        stage_norm(P, io, G, "final", xa, ALL_TILES, lambda ic: fin, lambda ic: None,
                   lambda c0, tw, ic: fm(outs["y"][:, c0:c0 + tw]), F32)
    if stages in ("all", "l0", "l1pre"):
        hp = io.scratch("hp", [D, 4608], F32)
        S = rw_scratch(io)
        with P.phase("zpad"):
            z = P.sb([128, 8, 64], F32)
            P.memset("vector", z[:], 0.0, ["z"])
            for k, o in enumerate((0, 64 + T, 4224, 4288 + C)):
                P.dma("sync", fm(hp[:, o:o + 64]), z[:], reads=["z"], writes=[("hpz", k)], sem=f"z{k}")

        def hdst(c0, tw, ic):
            o = 4288 if ic else 64 + c0
            return fm(hp[:, o:o + tw])

        def hbdst(c0, tw, ic):
            return fm(hb[:, c0:c0 + tw])

        def ms(l, kind, which):
            return lambda ic: mod_scalars(G, l, kind, ic)[which]

        stage_norm(P, io, G, "n_mix0", io["xin"], ALL_TILES, ms(0, 0, 0), ms(0, 0, 1), hdst, F32)
        stage_rwkv1(P, io, G, hp, S)
        stage_rwkv2(P, io, G, S, io["xin"], xa)
        stage_norm(P, io, G, "n_mlp0", xa, ALL_TILES, ms(0, 1, 0), ms(0, 1, 1), hbdst, BF16)
        stage_mlp(P, io, G, 0, ALL_TILES, xa, hb)
        if stages == "l0":
            outs["y"] = io.output("y", [D, TT])
            with P.phase("copyout"):
                P.dma("sync", outs["y"], xa, writes=["o"], sem="cpa")
        else:
            stage_norm(P, io, G, "n_mix1", xa, ALL_TILES, ms(1, 0, 0), ms(1, 0, 1), hbdst, BF16)
            with P.scope():
                QT = io.scratch("qtd", [D, T], BF16)
                Kz = [P.ssb([128, 4, TT], BF16, f"Kz{k}") for k in range(2)]
                VA = P.ssb([128, 34, 390], BF16, "VA")
                stage_qkv(P, io, G, hb, QT, Kz, VA)
                stage_attn(P, io, G, QT, Kz, VA, xa)
            if stages == "l1pre":
                outs["y"] = io.output("y", [D, TT])
                with P.phase("copyout"):
                    P.dma("sync", outs["y"], xa, writes=["o"], sem="cpa")
            else:
                stage_norm(P, io, G, "n_mlp1", xa, LAT_TILES, ms(1, 1, 0), ms(1, 1, 1), hbdst, BF16)
                stage_mlp(P, io, G, 1, LAT_TILES, xa, hb)
                outs["y"] = io.output("y", [D, T])
                fin = [G["vec"][:, 4, c:c + 1] for c in range(8)]
                stage_norm(P, io, G, "final", xa, LAT_TILES, lambda ic: fin, lambda ic: None,
                           lambda c0, tw, ic: fm(outs["y"][:, c0:c0 + tw]), F32)
    if stages == "t_rwkv":
        hp = io.scratch("hp", [D, 4608], F32)
        S = rw_scratch(io)
        with P.phase("zpad"):
            z = P.sb([128, 8, 64], F32)
            P.memset("vector", z[:], 0.0, ["z"])
            for k, o in enumerate((0, 64 + T, 4224, 4288 + C)):
                P.dma("sync", fm(hp[:, o:o + 64]), z[:], reads=["z"], writes=[("hpz", k)], sem=f"z{k}")
        def hdst(c0, tw, ic):
            o = 4288 if ic else 64 + c0
            return fm(hp[:, o:o + tw])
        stage_norm(P, io, G, "n_mix0", io["xin"], ALL_TILES,
                   lambda ic: mod_scalars(G, 0, 0, ic)[0], lambda ic: mod_scalars(G, 0, 0, ic)[1], hdst, F32)
        stage_rwkv1(P, io, G, hp, S)
        stage_rwkv2(P, io, G, S, io["xin"], xa)
        outs["y"] = io.output("y", [D, TT])
        with P.phase("copyout"):
            P.dma("sync", outs["y"], xa, writes=["o"], sem="cpa")
    P.close()
    return nc, io.used, list(outs.keys()), P


def fmv(v):
    return np.ascontiguousarray(np.asarray(v, np.float32).reshape(8, 128).T)


def host_consts():
    c = {}
    c["c_ident"] = np.eye(128, dtype=np.float32)
    c["c_ones"] = np.ones((128, 128), np.float32)
    blk = np.zeros((128, 128), np.float32)
    blk[:64, :64] = 1
    blk[64:, 64:] = 1
    c["c_bones"] = blk
    i = np.arange(64)
    us = (i[:, None] < i[None, :]).astype(np.float32)
    ui = (i[:, None] <= i[None, :]).astype(np.float32)
    m = np.zeros((128, 4, 128), np.float32)
    for k, mk in enumerate([us, ui, us.T, ui.T]):
        m[:64, k, :64] = mk
        m[64:, k, 64:] = mk
    c["c_masks"] = m
    Pm = np.zeros((128, 128), np.float32)
    for d in range(128):
        if d % 32 < 16:
            Pm[d, d + 16] = -1.0
        else:
            Pm[d, d - 16] = 1.0
    c["c_perm"] = np.ascontiguousarray(Pm.T)
    sel = np.zeros((128, 2, 128), np.float32)
    sel[64, 0, :] = 1.0
    sel[63, 1, :] = 1.0
    c["c_sel"] = sel
    rm = np.ones((128, 256), np.float32)
    rm[:, ::64] = 0
    c["c_rmask"] = rm
    t = np.arange(T)
    row = (t // 64).astype(np.float32)
    col = (t % 64).astype(np.float32)
    freqs = (np.float32(10000.0) ** (-np.arange(0, 32, 2, dtype=np.float32) / np.float32(32))).astype(np.float32)
    ang = np.zeros((64, T), np.float32)
    for d in range(64):
        pos = row if d < 32 else col
        ang[d] = pos * freqs[d % 16]
    c["cosT"] = np.ascontiguousarray(np.concatenate([np.cos(ang), np.cos(ang)], 0).astype(np.float32))
    c["sinT"] = np.ascontiguousarray(np.concatenate([np.sin(ang), np.sin(ang)], 0).astype(np.float32))
    return c


def host_inputs(inp, b):
    f = lambda k: np.asarray(inp[k], np.float32)
    d = {}
    d["xin"] = np.ascontiguousarray(np.concatenate([f("x")[b].T, f("ctx")[b].T], axis=1))
    d["cvec"] = np.ascontiguousarray(np.stack([fmv(f("c")[b]), fmv(f("c_ctx"))], axis=-1))
    return d


def host_shared(inp):
    f = lambda k: np.asarray(inp[k], np.float32)
    s = dict(host_consts())
    s["w_mod"] = f("w_mod")
    s["b_mod"] = f("b_mod")
    vl = [f("norm_mix")[0], f("norm_mix")[1], f("norm_mlp")[0], f("norm_mlp")[1], f("final_norm")]
    vl += [f("rwkv_mu")[0, j] for j in range(6)]
    vl += [f("rwkv_w0")[0, 0], f("rwkv_w0")[0, 1], f("rwkv_a0")[0, 0], f("rwkv_a0")[0, 1]]
    vl += [f("rwkv_k_k")[0], f("rwkv_k_a")[0], np.zeros(D, np.float32), f("rwkv_r_k")[0].reshape(-1)]
    vl += [np.tile(f("attn_q_norm")[0], 16), np.tile(f("attn_k_norm")[0], 16)]
    assert len(vl) == NV
    s["vecs"] = np.ascontiguousarray(np.stack([fmv(v) for v in vl], axis=1))
    s["mlp_w1"] = f("mlp_w1")
    s["mlp_w2"] = f("mlp_w2")
    for k in ("wr", "wk", "wv", "wo", "w1", "w2", "a1", "a2", "g1", "g2"):
        s["rwkv_" + k] = f("rwkv_" + k)[0]
    lw = f("rwkv_ln_w")[0].reshape(8, 2, 64)
    lb = f("rwkv_ln_b")[0].reshape(8, 2, 64)
    s["lnw_st"] = np.ascontiguousarray(np.repeat(lw.transpose(1, 0, 2), 64, axis=0))
    s["lnb_st"] = np.ascontiguousarray(np.repeat(lb.transpose(1, 0, 2), 64, axis=0))
    wqkv = f("attn_wqkv")[0]
    s["attn_wq"] = np.ascontiguousarray(wqkv[:, :1024])
    wk = wqkv[:, 1024:1280].reshape(D, 4, 64)
    s["attn_wkd"] = np.ascontiguousarray(np.concatenate([wk, wk], axis=2).reshape(D, 512))
    s["attn_wv"] = np.ascontiguousarray(wqkv[:, 1280:1536])
    s["attn_wo"] = f("attn_wo")[0]
    return s


_CACHE = {}


def kernel(**inputs):
    if "prog" not in _CACHE:
        _CACHE["prog"] = build("all")
    nc, used, outnames, _ = _CACHE["prog"]
    shared = host_shared(inputs)
    in_maps = []
    for b in range(NCORES):
        hi = host_inputs(inputs, b)
        hi.update(shared)
        in_maps.append({k: hi[k] for k in used})
    res = run_bass_kernel_spmd(nc, in_maps, core_ids=list(range(NCORES)))
    out = np.stack([np.ascontiguousarray(res.results[b]["y"].T) for b in range(NCORES)], axis=0)
    return out.astype(np.float32)
```

```python
from contextlib import ExitStack, contextmanager
import re as re_mod
import numpy as np
import concourse.bass as bass
import concourse.mybir as mybir
from concourse.bass_utils import run_bass_kernel_spmd

F32 = mybir.dt.float32
BF16 = mybir.dt.bfloat16
AF = mybir.ActivationFunctionType
ALU = mybir.AluOpType
AX = mybir.AxisListType

D = 1024
T = 4096
C = 256
TT = T + C
NCORES = 8
C0 = float(np.exp(-0.5))
NV = 21
ENGS = ("tensor", "vector", "scalar", "gpsimd", "sync")


class Prog:
    def __init__(self, nc):
        self.nc = nc
        self.ges = ExitStack()
        self.sems = {}
        self.cnt = {}
        self.dpool = {False: [], True: []}
        self.seen = {e: {} for e in ENGS}
        self.n = 0
        self.pes = None
        self.total_ops = 0

    def _alloc(self, es, fn, shape, dtype, name):
        self.n += 1
        return es.enter_context(fn(name or f"t{self.n}", list(shape), dtype))

    def gsb(self, shape, dtype, name=None):
        return self._alloc(self.ges, self.nc.sbuf_tensor, shape, dtype, name)

    def sb(self, shape, dtype, name=None):
        return self._alloc(self.pes, self.nc.sbuf_tensor, shape, dtype, name)

    @contextmanager
    def scope(self):
        self.ses = ExitStack()
        yield self
        self.ses.close()
        self.ses = None

    def ssb(self, shape, dtype, name=None):
        return self._alloc(self.ses, self.nc.sbuf_tensor, shape, dtype, name)

    def ps(self, shape, dtype, name=None):
        return self._alloc(self.pes, self.nc.psum_tensor, shape, dtype, name)

    @contextmanager
    def phase(self, name):
        self.ops = []
        self.last_w = {}
        self.readers = {}
        self.last_dma = {}
        self.pes = ExitStack()
        self.pname = name
        yield self
        self._emit()
        self.pes.close()
        self.pes = None

    _PSUM_RE = re_mod.compile(r"^(pp|pa|pb|pq|pf|ps\w*|pX|py|pS|ptr|pw)\d*$")

    ns = None
    ns_set = frozenset()

    def _deps(self, reads, writes):
        if self.ns is not None:
            reads = tuple((r, self.ns) if r in self.ns_set else r for r in reads)
            writes = tuple((w, self.ns) if w in self.ns_set else w for w in writes)
        extra = tuple(r for r in reads if isinstance(r, str) and self._PSUM_RE.match(r) and r not in writes)
        if extra:
            writes = tuple(writes) + extra
        deps = {}
        for r in reads:
            if r in self.last_w:
                deps.setdefault(self.last_w[r], set()).add("RAW")
        for w in writes:
            if w in self.last_w:
                deps.setdefault(self.last_w[w], set()).add("WAW")
            for rd in self.readers.get(w, ()):
                deps.setdefault(rd, set()).add("WAR")
        idx = len(self.ops)
        for r in reads:
            self.readers.setdefault(r, []).append(idx)
        for w in writes:
            self.last_w[w] = idx
            self.readers[w] = []
        return deps

    def op(self, eng, fn, reads=(), writes=()):
        deps = self._deps(tuple(reads), tuple(writes))
        self.ops.append(dict(eng=eng, fn=fn, deps=deps, dma=None))
        return len(self.ops) - 1

    def dma(self, queue, out, in_, reads=(), writes=(), sem=None):
        deps = self._deps(tuple(reads), tuple(writes))
        prev = self.last_dma.get(sem)
        if prev is not None:
            deps.setdefault(prev, set()).add("SER")
        idx = len(self.ops)
        self.last_dma[sem] = idx
        self.ops.append(dict(eng=queue, fn=lambda e: e.dma_start(out=out, in_=in_), deps=deps, dma=sem))
        return idx

    def _emit(self):
        nc = self.nc
        ops = self.ops
        if self.last_dma:
            ops.append(dict(eng="sync", fn=None, deps={i: {"FIN"} for i in self.last_dma.values()}, dma=None))
        self.total_ops += len(ops)

        def needs_wait(x, d, kinds):
            if d["dma"] is not None or x["dma"] is not None:
                return True
            if d["eng"] != x["eng"]:
                return True
            if x["eng"] == "tensor":
                return False
            return bool(kinds & {"RAW", "FIN"})

        signal = [False] * len(ops)
        for x in ops:
            for di, kinds in x["deps"].items():
                d = ops[di]
                if d["dma"] is None and needs_wait(x, d, kinds):
                    signal[di] = True
        dkeys = {}
        nk = {False: 0, True: 0}
        for o in ops:
            if o["dma"] is not None and o["dma"] not in dkeys:
                sw = o["eng"] == "gpsimd"
                dkeys[o["dma"]] = (sw, nk[sw])
                nk[sw] += 1
        for sw in (False, True):
            while len(self.dpool[sw]) < nk[sw]:
                h = self.ges.enter_context(nc.semaphore(f"dq{int(sw)}_{len(self.dpool[sw])}"))
                self.dpool[sw].append([h, 0])
        for e in ENGS:
            if e not in self.sems:
                self.sems[e] = self.ges.enter_context(nc.semaphore(f"e_{e}"))
        token = [None] * len(ops)
        for i, o in enumerate(ops):
            if o["dma"] is not None:
                dk = dkeys[o["dma"]]
                slot = self.dpool[dk[0]][dk[1]]
                slot[1] += 16
                token[i] = (("d", dk), slot[1])
            elif signal[i]:
                self.cnt[o["eng"]] = self.cnt.get(o["eng"], 0) + 1
                token[i] = (("e", o["eng"]), self.cnt[o["eng"]])
        per_eng = {e: [] for e in ENGS}
        for i, o in enumerate(ops):
            per_eng[o["eng"]].append(i)

        def semh(key):
            return self.dpool[key[1][0]][key[1][1]][0] if key[0] == "d" else self.sems[key[1]]

        def run(engname, eng):
            seen = self.seen[engname]
            for i in per_eng[engname]:
                o = ops[i]
                waits = {}
                for di, kinds in o["deps"].items():
                    d = ops[di]
                    if not needs_wait(o, d, kinds):
                        continue
                    key, val = token[di]
                    if waits.get(key, 0) < val:
                        waits[key] = val
                for key, val in waits.items():
                    if seen.get(key, 0) >= val:
                        continue
                    seen[key] = val
                    eng.wait_ge(semh(key), val)
                if o["fn"] is None:
                    continue
                ins = o["fn"](eng)
                if o["dma"] is not None:
                    ins.then_inc(semh(token[i][0]), 16)
                elif signal[i]:
                    ins.then_inc(self.sems[engname], 1)

        with nc.Block() as block:
            @block.sync
            def _(e):
                run("sync", e)

            @block.tensor
            def _(e):
                run("tensor", e)

            @block.vector
            def _(e):
                run("vector", e)

            @block.scalar
            def _(e):
                run("scalar", e)

            @block.gpsimd
            def _(e):
                run("gpsimd", e)

    def close(self):
        self.ges.close()

    def mm(self, out, lhsT, rhs, start, stop, r, w):
        self.op("tensor", lambda e: e.matmul(out, lhsT=lhsT, rhs=rhs, start=start, stop=stop), r, w)

    def tr(self, out, in_, ident, r, w):
        self.op("tensor", lambda e: e.transpose(out, in_, ident), r, w)

    def tt(self, eng, out, in0, in1, op, r, w):
        self.op(eng, lambda e: e.tensor_tensor(out=out, in0=in0, in1=in1, op=op), r, w)

    def ts(self, eng, out, in0, s1, s2, op0, op1, r, w):
        if op1 is None:
            self.op(eng, lambda e: e.tensor_scalar(out=out, in0=in0, scalar1=s1, scalar2=None, op0=op0), r, w)
        else:
            self.op(eng, lambda e: e.tensor_scalar(out=out, in0=in0, scalar1=s1, scalar2=s2, op0=op0, op1=op1), r, w)

    def stt(self, out, in0, scalar, in1, op0, op1, r, w):
        self.op("vector", lambda e: e.scalar_tensor_tensor(out=out, in0=in0, scalar=scalar, in1=in1, op0=op0, op1=op1), r, w)

    def act(self, out, in_, func, r, w, bias=None, scale=None):
        kw = {}
        if bias is not None:
            kw["bias"] = bias
        if scale is not None:
            kw["scale"] = scale
        self.op("scalar", lambda e: e.activation(out=out, in_=in_, func=func, **kw), r, w)

    def cp(self, eng, out, in_, r, w):
        if eng == "scalar":
            self.op(eng, lambda e: e.activation(out=out, in_=in_, func=AF.Copy), r, w)
        else:
            self.op(eng, lambda e: e.tensor_copy(out=out, in_=in_), r, w)

    def memset(self, eng, ap, val, w):
        self.op(eng, lambda e: e.memset(ap, val), (), w)


def fm(ap2d):
    return ap2d.rearrange("(c p) n -> p c n", p=128)


LAT_TILES = [(i * 512, 512, False) for i in range(8)]
ALL_TILES = LAT_TILES + [(T, 256, True)]


def stage_init(P, io, G):
    nc = P.nc
    G["identf"] = P.gsb([128, 128], F32, "identf")
    G["identb"] = P.gsb([128, 128], BF16, "identb")
    G["onesb"] = P.gsb([128, 128], BF16, "onesb")
    G["bones"] = P.gsb([128, 128], BF16, "bones")
    G["masks"] = P.gsb([128, 4, 128], BF16, "masks")
    G["perm"] = P.gsb([128, 128], BF16, "perm")
    G["rmask"] = P.gsb([128, 256], F32, "rmask")
    G["vec"] = P.gsb([128, NV, 8], F32, "vec")
    G["modv"] = P.gsb([128, 2, 6, 8, 2], F32, "modv")
    G["gg"] = P.gsb([128, 2, 2, 8, 2], F32, "gg")
    with P.phase("init"):
        P.dma("sync", G["identf"][:], io["c_ident"], writes=["identf"], sem="identf")
        P.dma("sync", G["rmask"][:], io["c_rmask"], writes=["rmask"], sem="rmask")
        P.dma("sync", G["vec"][:], io["vecs"], writes=["vec"], sem="vec")
        P.dma("gpsimd", G["identb"][:], io["c_ident"], writes=["identb"], sem="identb")
        P.dma("gpsimd", G["onesb"][:], io["c_ones"], writes=["onesb"], sem="onesb")
        P.dma("gpsimd", G["bones"][:], io["c_bones"], writes=["bones"], sem="bones")
        P.dma("gpsimd", G["masks"][:], io["c_masks"], writes=["masks"], sem="masks")
        P.dma("gpsimd", G["perm"][:], io["c_perm"], writes=["perm"], sem="perm")
        vec = G["vec"]
        P.ts("vector", vec[:, 17, :], vec[:, 16, :], -1.0, 1.0, ALU.mult, ALU.add, ["vec"], ["vec"])
        sv = P.sb([128, 8, 2], F32)
        svs = P.sb([128, 8, 2], F32)
        P.dma("sync", sv[:], io["cvec"], writes=["sv"], sem="sv")
        P.act(svs[:], sv[:], AF.Silu, ["sv"], ["svs"])
        brow = P.sb([2, 2 * 6144], F32)
        row = P.sb([2, 2 * 6144], F32)
        P.dma("sync", brow[:], io["b_mod"].rearrange("l n -> (l n)").partition_broadcast(2), writes=["brow"], sem="brow")
        wt = [P.sb([128, 8, 512], F32) for _ in range(2)]
        psr = [P.ps([128, 512], F32) for _ in range(2)]
        pst = P.ps([128, 512], F32)
        k = 0
        for l in range(2):
            for nb in range(12):
                b = k % 2
                k += 1
                P.dma("sync", wt[b][:], fm(io["w_mod"][l, :, nb * 512:(nb + 1) * 512]), writes=[f"wt{b}"], sem=f"wt{b}")
                for c in range(8):
                    P.mm(psr[b][0:2, :], svs[:, c, :], wt[b][:, c, :], c == 0, c == 7, ["svs", f"wt{b}"], [f"psr{b}"])
                o = l * 6144 + nb * 512
                P.tt("vector", row[:, o:o + 512], psr[b][0:2, :], brow[:, o:o + 512], ALU.add, [f"psr{b}", "brow"], ["row"])
        for l in range(2):
            for blk in range(48):
                o = l * 6144 + blk * 128
                P.tr(pst[:, l * 96 + blk * 2:l * 96 + blk * 2 + 2], row[0:2, o:o + 128], G["identf"][0:2, 0:2], ["row", "identf"], ["pst"])
        P.cp("vector", G["modv"][:].rearrange("p l m c j -> p (l m c j)"), pst[:, 0:192], ["pst"], ["modv"])
        modv, gg = G["modv"], G["gg"]
        for l in range(2):
            for kind in range(2):
                sc = modv[:, l, 1 + 3 * kind, :, :]
                nv = vec[:, (0 if kind == 0 else 2) + l, :].unsqueeze(2).broadcast_to([128, 8, 2])
                P.ts("vector", gg[:, l, kind, :, :], sc, 1.0, None, ALU.add, None, ["modv"], ["gg"])
                P.tt("vector", gg[:, l, kind, :, :], gg[:, l, kind, :, :], nv, ALU.mult, ["gg", "vec"], ["gg"])


def mod_scalars(G, l, kind, isctx):
    j = 1 if isctx else 0
    gains = [G["gg"][:, l, kind, c, j:j + 1] for c in range(8)]
    shifts = [G["modv"][:, l, 3 * kind, c, j:j + 1] for c in range(8)]
    gates = [G["modv"][:, l, 3 * kind + 2, c, j:j + 1] for c in range(8)]
    return gains, shifts, gates


def stage_norm(P, io, G, name, src, tiles, gains_fn, shifts_fn, dst_fn, out_dtype):
    with P.phase(name):
        xt = [P.sb([128, 8, 512], F32) for _ in range(2)]
        sq = P.sb([128, 8, 512], BF16)
        lnv = P.sb([128, 512], F32)
        rstd = P.sb([128, 512], F32)
        tmp = [P.sb([128, 512], F32) for _ in range(2)]
        ho = [P.sb([128, 8, 512], out_dtype) for _ in range(2)]
        ps = [P.ps([128, 512], F32) for _ in range(2)]

        def load(i):
            c0, tw, _ = tiles[i]
            b = i % 2
            P.dma("sync", xt[b][:, :, :tw], fm(src[:, c0:c0 + tw]), writes=[f"xt{b}"], sem=f"xt{b}")

        load(0)
        for i, (c0, tw, isctx) in enumerate(tiles):
            b = i % 2
            if i + 1 < len(tiles):
                load(i + 1)
            gains = gains_fn(isctx)
            shifts = shifts_fn(isctx)
            P.act(sq[:, :, :tw], xt[b][:, :, :tw], AF.Square, [f"xt{b}"], ["sq"])
            for c in range(8):
                P.mm(ps[b][:, :tw], G["onesb"][:], sq[:, c, :tw], c == 0, c == 7, ["sq", "onesb"], [f"ps{b}"])
            P.act(lnv[:, :tw], ps[b][:, :tw], AF.Ln, [f"ps{b}"], ["lnv"], bias=1e-6, scale=1.0 / D)
            P.act(rstd[:, :tw], lnv[:, :tw], AF.Exp, ["lnv"], ["rstd"], scale=-0.5)
            for c in range(8):
                if shifts is None:
                    P.stt(ho[b][:, c, :tw], xt[b][:, c, :tw], gains[c], rstd[:, :tw], ALU.mult, ALU.mult,
                          [f"xt{b}", "rstd", "vec", "gg"], [f"ho{b}"])
                else:
                    t = tmp[c % 2]
                    P.stt(t[:, :tw], xt[b][:, c, :tw], gains[c], rstd[:, :tw], ALU.mult, ALU.mult,
                          [f"xt{b}", "rstd", "vec", "gg"], [f"tmp{c % 2}"])
                    P.act(ho[b][:, c, :tw], t[:, :tw], AF.Identity, [f"tmp{c % 2}", "modv"], [f"ho{b}"], bias=shifts[c])
            P.dma("sync", dst_fn(c0, tw, isctx), ho[b][:, :, :tw], reads=[f"ho{b}"], writes=[("dst", i)], sem=f"ho{b}")


def stage_mlp(P, io, G, l, tiles, xa, hb):
    for half in range(2):
        with P.phase(f"mlp{l}{half}"):
            w1 = P.sb([128, 8, 2048], BF16)
            w2 = P.sb([128, 16, 1024], BF16)
            for q in range(2):
                P.dma("gpsimd", w1[:, :, q * 1024:(q + 1) * 1024],
                      fm(io["mlp_w1"][l, :, half * 2048 + q * 1024: half * 2048 + (q + 1) * 1024]), writes=["w1"], sem=f"w1{q}")
                P.dma("gpsimd", w2[:, q * 8:(q + 1) * 8, :],
                      io["mlp_w2"][l, half * 2048 + q * 1024: half * 2048 + (q + 1) * 1024, :].rearrange("(f p) n -> p f n", p=128),
                      writes=["w2"], sem=f"w2{q}")
            xt = [P.sb([128, 8, 512], F32) for _ in range(2)]
            ht = [P.sb([128, 8, 512], BF16) for _ in range(2)]
            h1 = P.sb([128, 16, 512], BF16)
            r1 = [P.sb([128, 512], F32) for _ in range(2)]
            ps = [P.ps([128, 512], F32) for _ in range(4)]

            def load(i):
                c0, tw, _ = tiles[i]
                b = i % 2
                P.dma("sync", ht[b][:, :, :tw], fm(hb[:, c0:c0 + tw]), writes=[f"ht{b}"], sem=f"ht{b}")
                P.dma("sync", xt[b][:, :, :tw], fm(xa[:, c0:c0 + tw]), reads=[("xa", i)], writes=[f"xt{b}"], sem=f"xt{b}")

            load(0)
            for i, (c0, tw, isctx) in enumerate(tiles):
                b = i % 2
                if i + 1 < len(tiles):
                    load(i + 1)
                _, _, gates = mod_scalars(G, l, 1, isctx)
                for fc in range(16):
                    pb = fc % 2
                    for c in range(8):
                        P.mm(ps[pb][:, :tw], w1[:, c, fc * 128:(fc + 1) * 128], ht[b][:, c, :tw], c == 0, c == 7,
                             ["w1", f"ht{b}"], [f"ps{pb}"])
                    P.act(r1[pb][:, :tw], ps[pb][:, :tw], AF.Relu, [f"ps{pb}"], [f"r1{pb}"])
                    P.tt("gpsimd", h1[:, fc, :tw], r1[pb][:, :tw], r1[pb][:, :tw], ALU.mult, [f"r1{pb}"], [("h1", fc)])
                for oc in range(8):
                    pb = 2 + oc % 2
                    for fc in range(16):
                        P.mm(ps[pb][:, :tw], w2[:, fc, oc * 128:(oc + 1) * 128], h1[:, fc, :tw], fc == 0, fc == 15,
                             ["w2", ("h1", fc)], [f"ps{pb}"])
                    P.stt(xt[b][:, oc, :tw], ps[pb][:, :tw], gates[oc], xt[b][:, oc, :tw], ALU.mult, ALU.add,
                          [f"ps{pb}", f"xt{b}", "modv"], [f"xt{b}"])
                P.dma("sync", fm(xa[:, c0:c0 + tw]), xt[b][:, :, :tw], reads=[f"xt{b}"], writes=[("xa", i)], sem=f"xt{b}")


RW_ORDER1 = [(True, 0)] + [(False, i) for i in range(16)]
RW_ORDER2 = [(True, 0)] + [(False, i) for i in range(15, -1, -1)]


def rw_scratch(io):
    S = {}
    S["yp"] = io.scratch("rw_yp", [17, 8, 128, 256], F32)
    S["sadd"] = io.scratch("rw_sadd", [17, 8, 128, 256], F32)
    S["vst"] = io.scratch("rw_vst", [17, 8, 128, 256], F32)
    S["gst"] = io.scratch("rw_gst", [17, 8, 128, 256], F32)
    S["gyb"] = io.scratch("rw_gyb", [17, 8, 128, 512], BF16)
    S["gsb"] = io.scratch("rw_gsb", [17, 8, 128, 512], BF16)
    S["gamb"] = io.scratch("rw_gamb", [17, 128, 32], F32)
    S["bon"] = io.scratch("rw_bon", [17, 128, 32], F32)
    S["ops"] = io.scratch("rw_ops", [17, 8, 128, 2048], BF16)
    S["vb"] = io.scratch("rw_vb", [17, 8, 128, 256], BF16)
    S["gam"] = io.scratch("rw_gam", [17, 8, 128, 8], F32)
    return S


def stage_rwkv1(P, io, G, hp, S, dbg=None):
    vec, masks, identb, identf, bones, onesb, rmask = (G[k] for k in ("vec", "masks", "identb", "identf", "bones", "onesb", "rmask"))
    with P.phase("rwkv1"):
        wr = P.sb([128, 8, 1024], BF16)
        wk = P.sb([128, 8, 1024], BF16)
        wv = P.sb([128, 8, 1024], BF16)
        for w, nm in ((wr, "rwkv_wr"), (wk, "rwkv_wk"), (wv, "rwkv_wv")):
            P.dma("gpsimd", w[:], fm(io[nm]), writes=[nm], sem=nm)
        lw1 = P.sb([128, 8, 128], BF16)
        la1 = P.sb([128, 8, 128], BF16)
        g1 = P.sb([128, 8, 128], BF16)
        for d in range(2):
            P.dma("gpsimd", lw1[:, :, d * 64:(d + 1) * 64], io["rwkv_w1"][d].rearrange("(c p) j -> p c j", p=128), writes=["lw1"], sem=f"lw1{d}")
            P.dma("gpsimd", la1[:, :, d * 64:(d + 1) * 64], io["rwkv_a1"][d].rearrange("(c p) j -> p c j", p=128), writes=["la1"], sem=f"la1{d}")
        P.dma("gpsimd", g1[:], io["rwkv_g1"].rearrange("(c p) j -> p c j", p=128), writes=["g1"], sem="g1")
        w2s = P.sb([128, 1024], BF16)
        a2s = P.sb([128, 1024], BF16)
        g2 = P.sb([128, 1024], BF16)
        P.dma("gpsimd", w2s[:], io["rwkv_w2"].rearrange("d j f -> (d j) f"), writes=["w2s"], sem="w2s")
        P.dma("gpsimd", a2s[:], io["rwkv_a2"].rearrange("d j f -> (d j) f"), writes=["a2s"], sem="a2s")
        P.dma("gpsimd", g2[:], io["rwkv_g2"], writes=["g2"], sem="g2")

        hh = P.sb([128, 8, 384], F32)
        xx = P.sb([128, 8, 256], F32)
        xr = P.sb([128, 8, 256], BF16)
        xk = P.sb([128, 8, 256], BF16)
        xv = P.sb([128, 8, 256], BF16)
        xrot = P.sb([128, 8, 256], BF16)
        lwt = P.sb([128, 256], BF16)
        lat = P.sb([128, 256], BF16)
        sg = P.sb([128, 256], BF16)
        f32t = {}
        for nm in ("r", "k", "sw0", "sw1", "ag0", "ag1", "kq", "lnv", "rs", "kkn", "fac", "kd0", "kd1", "b0", "b1",
                   "L", "Lx", "Lb", "E1", "E2", "E3", "ks"):
            f32t[nm] = P.sb([128, 256], F32, "t_" + nm)
        sqb = P.sb([128, 256], BF16)
        RK = P.sb([128, 4, 2, 64], BF16)
        VTbd = P.sb([128, 4, 128], F32)
        GTbd = P.sb([128, 4, 128], F32)
        Vf = P.sb([128, 4, 64], F32)
        Gf = P.sb([128, 4, 64], F32)
        YPs = P.sb([128, 4, 64], F32)
        SAs = P.sb([128, 4, 64], F32)
        gamb_t = P.sb([128, 8, 4], F32)
        bon_t = P.sb([128, 8, 4], F32)
        Sf = P.sb([128, 8, 64], BF16)
        ARq = [[P.sb([128, 4, 2, 128], BF16, f"AR{q}{d}") for d in range(2)] for q in range(2)]
        KTq = [[P.sb([128, 4, 128], BF16, f"KT{q}{d}") for d in range(2)] for q in range(2)]
        BTq = [[P.sb([128, 4, 128], BF16, f"BT{q}{d}") for d in range(2)] for q in range(2)]
        Vbq = [P.sb([128, 4, 64], BF16, f"Vb{q}") for q in range(3)]
        gamq = [[P.sb([128, 4], F32, f"gam{q}{d}") for d in range(2)] for q in range(3)]
        inv = []
        for d in range(2):
            st = {}
            for nm, shp in (("Atok", [128, 4, 128]), ("Btok", [128, 4, 128]), ("MQ", [128, 4, 256]), ("MWa", [128, 4, 2, 128]),
                            ("MWb", [128, 4, 2, 128]), ("MTa", [128, 4, 128]), ("MTb", [128, 4, 128])):
                st[nm] = P.sb(shp, BF16, f"i{d}_{nm}")
            inv.append(st)
        fin = []
        for q in range(2):
            row = []
            for d in range(2):
                st = {}
                for nm, shp in (("Ktok", [128, 4, 128]), ("NP", [128, 4, 256]), ("XW", [128, 4, 256]), ("NVb", [128, 4, 64]),
                                ("GY", [128, 4, 128]), ("GS", [128, 4, 128])):
                    st[nm] = P.sb(shp, BF16, f"f{q}{d}_{nm}")
                row.append(st)
            fin.append(row)
        ppt = [P.ps([128, 512], F32) for _ in range(2)]
        pp = [t_[:, 0:256] for t_ in ppt]
        pf = P.ps([128, 512], F32)
        pb = [P.ps([128, 512], F32) for _ in range(5)]
        cnt = {"pp": 0, "pb": 0}
        nmod = {"pp": 2, "pb": 5}

        def nxt(kind):
            i = cnt[kind] % nmod[kind]
            cnt[kind] += 1
            return i

        for q in range(2):
            for d in range(2):
                P.memset("gpsimd", ARq[q][d][:], 0.0, [f"AR{q}{d}"])
                P.memset("gpsimd", KTq[q][d][:], 0.0, [f"KT{q}{d}"])
                P.memset("gpsimd", BTq[q][d][:], 0.0, [f"BT{q}{d}"])
        P.memset("gpsimd", RK[:], 0.0, ["RK"])
        P.memset("gpsimd", VTbd[:], 0.0, ["VTbd"])
        P.memset("gpsimd", GTbd[:], 0.0, ["GTbd"])
        P.memset("gpsimd", Sf[:], 0.0, [("Sf", p) for p in range(8)])

        def v3(ap):
            return ap.rearrange("p (u s) -> p u s", s=64)

        def u128(ap):
            return ap.rearrange("p (u x) -> p u x", x=128)

        def load_hh(ti):
            isctx, idx = RW_ORDER1[ti]
            off = 4288 if isctx else 64 + 256 * idx
            P.dma("sync", hh[:], fm(hp[:, off - 64: off + 320]), writes=["hh"], sem="hh")

        def proj8(w_cols_fn, xb, bn, extra_r):
            i = nxt("pp")
            for c in range(8):
                P.mm(pp[i], w_cols_fn(c), xb[:, c, :], c == 0, c == 7, [(bn, c)] + extra_r, [f"pp{i}"])
            return i

        def tprep(ti):
            isctx, idx = RW_ORDER1[ti]
            hc = hh[:, :, 64:320]
            XXW = [("xx", c) for c in range(8)]
            if not isctx:
                h4 = hh[:, :, 64:320].rearrange("p c (r w) -> p c r w", w=64)
                x4 = xx[:].rearrange("p c (r w) -> p c r w", w=64)
                P.tt("vector", x4[:, 0:2, :, 1:64], h4[:, 0:2, :, 0:63], h4[:, 0:2, :, 1:64], ALU.subtract, ["hh"], XXW[0:2])
                P.ts("gpsimd", x4[:, 0:2, :, 0:1], h4[:, 0:2, :, 0:1], -1.0, 0.0, ALU.mult, ALU.add, ["hh"], [("xxe", 0)])
                P.tt("vector", x4[:, 2:4, :, 0:63], h4[:, 2:4, :, 1:64], h4[:, 2:4, :, 0:63], ALU.subtract, ["hh"], XXW[2:4])
                P.ts("gpsimd", x4[:, 2:4, :, 63:64], h4[:, 2:4, :, 63:64], -1.0, 0.0, ALU.mult, ALU.add, ["hh"], [("xxe", 1)])
                P.tt("gpsimd", xx[:, 4:6, :], hh[:, 4:6, 0:256], hh[:, 4:6, 64:320], ALU.subtract, ["hh"], XXW[4:6])
                P.tt("gpsimd", xx[:, 6:8, :], hh[:, 6:8, 128:384], hh[:, 6:8, 64:320], ALU.subtract, ["hh"], XXW[6:8])
            else:
                P.tt("vector", xx[:, 0:4, :], hh[:, 0:4, 63:319], hh[:, 0:4, 64:320], ALU.subtract, ["hh"], XXW[0:4] + [("xxe", 0)])
                P.tt("gpsimd", xx[:, 4:8, :], hh[:, 4:8, 65:321], hh[:, 4:8, 64:320], ALU.subtract, ["hh"], XXW[4:8] + [("xxe", 1)])
            yield

            def mk_xj(j, buf, bn):
                for c in range(8):
                    P.stt(buf[:, c, :], xx[:, c, :], vec[:, 5 + j, c:c + 1], hc[:, c, :], ALU.mult, ALU.add,
                          [("xx", c), ("xxe", 0), ("xxe", 1), "hh", "vec"], [(bn, c)])

            mk_xj(1, xrot, "xrot")
            yield
            i = proj8(lambda c: lw1[:, c, :], xrot, "xrot", ["lw1"])
            P.act(lwt[:], pp[i], AF.Tanh, [f"pp{i}"], ["lwt"])
            yield
            mk_xj(4, xrot, "xrot")
            yield
            i = proj8(lambda c: la1[:, c, :], xrot, "xrot", ["la1"])
            P.cp("scalar", lat[:], pp[i], [f"pp{i}"], ["lat"])
            yield
            mk_xj(5, xrot, "xrot")
            yield
            i = proj8(lambda c: g1[:, c, :], xrot, "xrot", ["g1"])
            P.act(sg[:], pp[i], AF.Sigmoid, [f"pp{i}"], ["sg"])
            yield
            mk_xj(0, xr, "xr")
            yield
            mk_xj(2, xk, "xk")
            yield
            mk_xj(3, xv, "xv")
            if ti + 1 < len(RW_ORDER1):
                load_hh(ti + 1)
            yield

        def prep(ti, oc, q, z):
            isctx, idx = RW_ORDER1[ti]
            tg = 16 if isctx else idx
            cs = slice(oc * 128, (oc + 1) * 128)
            t = f32t
            AR, KT, BT, Vb, gam = ARq[q], KTq[q], BTq[q], Vbq[z], gamq[z]
            i = proj8(lambda c: wr[:, c, cs], xr, "xr", ["rwkv_wr"])
            P.cp("scalar", t["r"][:], pp[i], [f"pp{i}"], ["r"])
            i = proj8(lambda c: wk[:, c, cs], xk, "xk", ["rwkv_wk"])
            P.cp("scalar", t["k"][:], pp[i], [f"pp{i}"], ["k"])
            i = proj8(lambda c: wv[:, c, cs], xv, "xv", ["rwkv_wv"])
            vt4 = VTbd[:].rearrange("p u (h s) -> p u h s", h=2)
            for h2 in range(2):
                sl = slice(h2 * 64, (h2 + 1) * 64)
                P.cp("scalar", vt4[sl, :, h2, :], v3(pp[i][sl, :]), [f"pp{i}"], ["VTbd"])
            i = nxt("pp")
            P.mm(pp[i], g2[:, cs], sg[:], True, True, ["g2", "sg"], [f"pp{i}"])
            gt4 = GTbd[:].rearrange("p u (h s) -> p u h s", h=2)
            for h2 in range(2):
                sl = slice(h2 * 64, (h2 + 1) * 64)
                P.cp("scalar", gt4[sl, :, h2, :], v3(pp[i][sl, :]), [f"pp{i}"], ["GTbd"])
            yield
            j = nxt("pb")
            for u in range(4):
                P.tr(pb[j][:, u * 128:(u + 1) * 128], VTbd[:, u, :], identf[:], ["VTbd", "identf"], [f"pb{j}"])
            pv = u128(pb[j][:])
            for h2 in range(2):
                sl = slice(h2 * 64, (h2 + 1) * 64)
                P.cp("scalar", Vf[sl, :, :], pv[sl, :, h2 * 64:(h2 + 1) * 64], [f"pb{j}"], ["Vf"])
            P.cp("gpsimd", Vb[:], Vf[:], ["Vf"], [f"Vb{z}"])
            P.dma("sync", S["vst"][tg, oc].rearrange("p (u s) -> p u s", s=64), Vf[:], reads=["Vf"], writes=[("vst", tg, oc)], sem="Vf")
            j = nxt("pb")
            for u in range(4):
                P.tr(pb[j][:, u * 128:(u + 1) * 128], GTbd[:, u, :], identf[:], ["GTbd", "identf"], [f"pb{j}"])
            pv = u128(pb[j][:])
            for h2 in range(2):
                sl = slice(h2 * 64, (h2 + 1) * 64)
                P.cp("scalar", Gf[sl, :, :], pv[sl, :, h2 * 64:(h2 + 1) * 64], [f"pb{j}"], ["Gf"])
            P.dma("sync", S["gst"][tg, oc].rearrange("p (u s) -> p u s", s=64), Gf[:], reads=["Gf"], writes=[("gst", tg, oc)], sem="Gf")
            yield
            for d in range(2):
                dl = slice(d * 64, (d + 1) * 64)
                i = nxt("pp")
                P.mm(pp[i], w2s[dl, cs], lwt[dl, :], True, True, ["w2s", "lwt"], [f"pp{i}"])
                P.act(t[f"sw{d}"][:], pp[i], AF.Sigmoid, [f"pp{i}", "vec"], [f"sw{d}"], bias=vec[:, 11 + d, oc:oc + 1])
                i = nxt("pp")
                P.mm(pp[i], a2s[dl, cs], lat[dl, :], True, True, ["a2s", "lat"], [f"pp{i}"])
                P.act(t[f"ag{d}"][:], pp[i], AF.Sigmoid, [f"pp{i}", "vec"], [f"ag{d}"], bias=vec[:, 13 + d, oc:oc + 1])
            yield
            P.ts("vector", t["kq"][:], t["k"][:], vec[:, 15, oc:oc + 1], None, ALU.mult, None, ["k", "vec"], ["kq"])
            P.act(sqb[:], t["kq"][:], AF.Square, ["kq"], ["sqb"])
            i = nxt("pp")
            P.mm(pp[i], bones[:], sqb[:], True, True, ["bones", "sqb"], [f"pp{i}"])
            P.act(t["lnv"][:], pp[i], AF.Ln, [f"pp{i}"], ["lnv"], bias=1e-12)
            P.act(t["rs"][:], t["lnv"][:], AF.Exp, ["lnv"], ["rs"], scale=-0.5)
            P.tt("gpsimd", t["kkn"][:], t["kq"][:], t["rs"][:], ALU.mult, ["kq", "rs"], ["kkn"])
            for d in range(2):
                sw, ag, kd, bb = t[f"sw{d}"], t[f"ag{d}"], t[f"kd{d}"], t[f"b{d}"]
                EE = "gpsimd" if d == 0 else "vector"
                P.ts(EE, t["fac"][:], ag[:], vec[:, 16, oc:oc + 1], vec[:, 17, oc:oc + 1], ALU.mult, ALU.add, [f"ag{d}", "vec"], ["fac"])
                P.tt(EE, kd[:], t["k"][:], t["fac"][:], ALU.mult, ["k", "fac"], [f"kd{d}"])
                P.tt(EE, bb[:], t["kkn"][:], ag[:], ALU.mult, ["kkn", f"ag{d}"], [f"b{d}"])
                P.op("vector", lambda e, sw=sw: e.tensor_tensor_scan(out=t["L"][:], data0=rmask[:], data1=sw[:], initial=0.0,
                                                                      op0=ALU.mult, op1=ALU.add), [f"sw{d}", "rmask"], ["L"])
                L3 = v3(t["L"][:])
                if d == 0:
                    P.tt(EE, t["Lx"][:], t["L"][:], sw[:], ALU.subtract, ["L", f"sw{d}"], ["Lx"])
                    Li, Lin = t["L"], "L"
                else:
                    P.tt(EE, v3(t["Lx"][:]), L3[:, :, 63:64].broadcast_to([128, 4, 64]), L3, ALU.subtract, ["L"], ["Lx"])
                    P.tt(EE, t["Lb"][:], t["Lx"][:], sw[:], ALU.add, ["Lx", f"sw{d}"], ["Lb"])
                    Li, Lin = t["Lb"], "Lb"
                P.act(t["E1"][:], Li[:], AF.Exp, [Lin], ["E1"], scale=-C0)
                P.act(t["E3"][:], Li[:], AF.Exp, [Lin], ["E3"], scale=C0)
                P.act(t["E2"][:], t["Lx"][:], AF.Exp, ["Lx"], ["E2"], scale=-C0)
                ar5 = AR[d][:].rearrange("p u a (h s) -> p u a h s", h=2)
                kt4 = KT[d][:].rearrange("p u (h s) -> p u h s", h=2)
                bt4 = BT[d][:].rearrange("p u (h s) -> p u h s", h=2)
                for h2 in range(2):
                    sl = slice(h2 * 64, (h2 + 1) * 64)
                    P.stt(ar5[sl, :, 0, h2, :], v3(t["kkn"][sl, :]), -1.0, v3(t["E2"][sl, :]), ALU.mult, ALU.mult, ["kkn", "E2"], [f"AR{q}{d}"])
                    P.tt(EE, ar5[sl, :, 1, h2, :], v3(t["r"][sl, :]), v3(t["E1"][sl, :]), ALU.mult, ["r", "E1"], [f"AR{q}{d}"])
                    P.tt(EE, kt4[sl, :, h2, :], v3(kd[sl, :]), v3(t["E3"][sl, :]), ALU.mult, [f"kd{d}", "E3"], [f"KT{q}{d}"])
                    P.tt(EE, bt4[sl, :, h2, :], v3(bb[sl, :]), v3(t["E3"][sl, :]), ALU.mult, [f"b{d}", "E3"], [f"BT{q}{d}"])
                E13 = v3(t["E1"][:])
                gsrc = E13[:, :, 63] if d == 0 else E13[:, :, 0]
                P.cp("vector", gam[d][:], gsrc, ["E1"], [f"gam{z}{d}"])
                if d == 1:
                    P.cp("gpsimd", gamb_t[:, oc, :], gam[1][:], [f"gam{z}1"], ["gamb_t"])
                yield
            P.tt("gpsimd", t["ks"][:], t["kd0"][:], t["kd1"][:], ALU.add, ["kd0", "kd1"], ["ks"])
            for h2 in range(2):
                sl = slice(h2 * 64, (h2 + 1) * 64)
                P.stt(RK[sl, :, h2, :], v3(t["r"][sl, :]), vec[sl, 18, oc:oc + 1], v3(t["ks"][sl, :]), ALU.mult, ALU.mult, ["r", "ks", "vec"], ["RK"])
            i = nxt("pp")
            for u in range(4):
                P.mm(pp[i][:, u:u + 1], RK[:, u, :, :].rearrange("p h s -> p (h s)"), onesb[:, 0:1], True, True, ["RK", "onesb"], [f"pp{i}"])
            P.cp("scalar", bon_t[:, oc, :], pp[i][:, 0:4], [f"pp{i}"], ["bon_t"])
            if oc == 7:
                P.dma("sync", S["gamb"][tg], gamb_t[:].rearrange("p a b -> p (a b)"), reads=["gamb_t"], writes=[("gamb", tg)], sem="gamb_t")
                P.dma("sync", S["bon"][tg], bon_t[:].rearrange("p a b -> p (a b)"), reads=["bon_t"], writes=[("bon", tg)], sem="bon_t")
            yield

        def chain(ti, oc, q, d, z):
            AR, KT, BT, Vb = ARq[q][d], KTq[q][d], BTq[q][d], Vbq[z]
            ARn, KTn, BTn, Vbn = f"AR{q}{d}", f"KT{q}{d}", f"BT{q}{d}", f"Vb{z}"
            iv, fn = inv[d], fin[q][d]
            IR = lambda nm: f"i{d}_{nm}"
            FR = lambda nm: f"f{q}{d}_{nm}"
            mS, mC = (0, 2) if d == 0 else (2, 0)
            mSI = masks[:, mS:mS + 2, :].rearrange("p a b -> p (a b)").unsqueeze(1).broadcast_to([128, 4, 256])
            mCb = masks[:, mC, :].unsqueeze(1).broadcast_to([128, 4, 128])
            idb = identb[:].unsqueeze(1).broadcast_to([128, 4, 128])
            for src, srcn, dst, dstn in ((AR[:, :, 0, :], ARn, iv["Atok"], IR("Atok")), (BT[:], BTn, iv["Btok"], IR("Btok")),
                                         (KT[:], KTn, fn["Ktok"], FR("Ktok"))):
                j = nxt("pb")
                pbt = pb[j][:].bitcast(BF16)
                for u in range(4):
                    P.tr(pbt[:, u * 128:(u + 1) * 128], src[:, u, :], identb[:], [srcn, "identb"], [f"pb{j}"])
                P.cp("scalar", dst[:].rearrange("p u x -> p (u x)"), pbt[:, 0:512], [f"pb{j}"], [dstn])
            mSb = masks[:, mS, :].unsqueeze(1).broadcast_to([128, 4, 128])
            mIb = masks[:, mS + 1, :].unsqueeze(1).broadcast_to([128, 4, 128])

            def two_bank(mm_fn):
                j0, j1 = nxt("pb"), nxt("pb")
                for u in range(4):
                    mm_fn(u, pb[j0][:, u * 128:(u + 1) * 128], f"pb{j0}", pb[j1][:, u * 128:(u + 1) * 128], f"pb{j1}")
                return j0, j1

            for lhs, lhsn, dst, dstn in ((BT, BTn, iv["MQ"], IR("MQ")), (KT, KTn, fn["NP"], FR("NP"))):
                def mm_ab(u, o0, n0, o1, n1, lhs=lhs, lhsn=lhsn):
                    P.mm(o0, lhs[:, u, :], AR[:, u, 0, :], True, True, [lhsn, ARn], [n0])
                    P.mm(o1, lhs[:, u, :], AR[:, u, 1, :], True, True, [lhsn, ARn], [n1])
                j0, j1 = two_bank(mm_ab)
                P.tt("vector", dst[:, :, 0:128], u128(pb[j0][:]), mSb, ALU.mult, [f"pb{j0}", "masks"], [dstn])
                P.tt("vector", dst[:, :, 128:256], u128(pb[j1][:]), mIb, ALU.mult, [f"pb{j1}", "masks"], [dstn])
            j = nxt("pb")
            for u in range(4):
                P.mm(pb[j][:, u * 128:(u + 1) * 128], AR[:, u, 0, :], BT[:, u, :], True, True, [ARn, BTn], [f"pb{j}"])
            cur, curn, nx, nxn = iv["MWa"], IR("MWa"), iv["MWb"], IR("MWb")
            P.tt("vector", cur[:, :, 0, :], u128(pb[j][:]), mCb, ALU.mult, [f"pb{j}", "masks"], [curn])
            yield
            j = nxt("pb")
            for u in range(4):
                P.mm(pb[j][:, u * 128:(u + 1) * 128], iv["MQ"][:, u, 0:128], cur[:, u, 0, :], True, True, [IR("MQ"), curn], [f"pb{j}"])
            P.cp("scalar", nx[:, :, 0, :], u128(pb[j][:]), [f"pb{j}"], [nxn])
            P.tt("gpsimd", nx[:, :, 1, :], cur[:, :, 0, :], idb, ALU.add, [curn, "identb"], [nxn])
            j = nxt("pb")
            for u in range(4):
                P.mm(pb[j][:, u * 128:(u + 1) * 128], cur[:, u, 0, :], iv["MQ"][:, u, 0:128], True, True, [IR("MQ"), curn], [f"pb{j}"])
            curT, curTn, nxT, nxTn = iv["MTa"], IR("MTa"), iv["MTb"], IR("MTb")
            P.cp("scalar", curT[:], u128(pb[j][:]), [f"pb{j}"], [curTn])
            cur, curn, nx, nxn = nx, nxn, cur, curn
            yield
            for lev in range(1, 5):
                def mm_lev(u, o0, n0, o1, n1, cur=cur, curn=curn, curT=curT, curTn=curTn):
                    P.mm(o0, curT[:, u, :], cur[:, u, 0, :], True, True, [curTn, curn], [n0])
                    P.mm(o1, curT[:, u, :], cur[:, u, 1, :], True, True, [curTn, curn], [n1])
                j0, j1 = two_bank(mm_lev)
                P.cp("scalar", nx[:, :, 0, :], u128(pb[j0][:]), [f"pb{j0}"], [nxn])
                P.tt("vector", nx[:, :, 1, :], u128(pb[j1][:]), cur[:, :, 1, :], ALU.add, [f"pb{j1}", curn], [nxn])
                j = nxt("pb")
                for u in range(4):
                    P.mm(pb[j][:, u * 128:(u + 1) * 128], cur[:, u, 0, :], curT[:, u, :], True, True, [curn, curTn], [f"pb{j}"])
                P.cp("scalar", nxT[:], u128(pb[j][:]), [f"pb{j}"], [nxTn])
                cur, curn, nx, nxn = nx, nxn, cur, curn
                curT, curTn, nxT, nxTn = nxT, nxTn, curT, curTn
                yield
            j = nxt("pb")
            for u in range(4):
                P.mm(pb[j][:, u * 128:(u + 1) * 128], curT[:, u, :], cur[:, u, 1, :], True, True, [curTn, curn], [f"pb{j}"])
            P.tt("vector", nx[:, :, 1, :], u128(pb[j][:]), cur[:, :, 1, :], ALU.add, [f"pb{j}", curn], [nxn])
            W6, W6n = nx, nxn
            j = nxt("pb")
            for u in range(4):
                P.mm(pb[j][:, u * 64:(u + 1) * 64], fn["NP"][:, u, 0:128], Vb[:, u, :], True, True, [FR("NP"), Vbn], [f"pb{j}"])
            P.cp("scalar", fn["NVb"][:].rearrange("p u x -> p (u x)"), pb[j][:, 0:256], [f"pb{j}"], [FR("NVb")])
            yield

            def mm_d(u, o0, n0, o1, n1):
                P.mm(o0, W6[:, u, 1, :], iv["MQ"][:, u, 128:256], True, True, [W6n, IR("MQ")], [n0])
                P.mm(o1, W6[:, u, 1, :], iv["Btok"][:, u, :], True, True, [W6n, IR("Btok")], [n1])
            j0, j1 = two_bank(mm_d)
            P.cp("scalar", fn["XW"][:, :, 0:128], u128(pb[j0][:]), [f"pb{j0}"], [FR("XW")])
            P.cp("vector", fn["XW"][:, :, 128:256], u128(pb[j1][:]), [f"pb{j1}"], [FR("XW")])
            yield

            def mm_f(u, o0, n0, o1, n1):
                P.mm(o0, iv["Atok"][:, u, :], fn["XW"][:, u, 0:128], True, True, [IR("Atok"), FR("XW")], [n0])
                P.mm(o1, iv["Atok"][:, u, :], fn["XW"][:, u, 128:256], True, True, [IR("Atok"), FR("XW")], [n1])
            j0, j1 = two_bank(mm_f)
            P.tt("vector", fn["GY"][:], u128(pb[j0][:]), AR[:, :, 1, :], ALU.add, [f"pb{j0}", ARn], [FR("GY")])
            P.tt("vector", fn["GS"][:], u128(pb[j1][:]), idb, ALU.add, [f"pb{j1}", "identb"], [FR("GS")])
            yield

        def finish(ti, oc, q, z):
            isctx, idx = RW_ORDER1[ti]
            tg = 16 if isctx else idx
            sf, sb_ = fin[q]
            F0 = lambda nm: f"f{q}0_{nm}"
            F1 = lambda nm: f"f{q}1_{nm}"
            Vb, Vbn, gam = Vbq[z], f"Vb{z}", gamq[z]
            SFR = ("Sf", oc)
            for u in range(4):
                yo = pf[:, u * 64:(u + 1) * 64]
                P.mm(yo, sf["NP"][:, u, 128:256], Vb[:, u, :], True, False, [F0("NP"), Vbn], ["pf"])
                P.mm(yo, sf["XW"][:, u, 0:128], sf["NVb"][:, u, :], False, False, [F0("XW"), F0("NVb")], ["pf"])
                P.mm(yo, sb_["NP"][:, u, 128:256], Vb[:, u, :], False, False, [F1("NP"), Vbn], ["pf"])
                P.mm(yo, sb_["XW"][:, u, 0:128], sb_["NVb"][:, u, :], False, False, [F1("XW"), F1("NVb")], ["pf"])
                P.mm(yo, sf["GY"][:, u, :], Sf[:, oc, :], False, True, [F0("GY"), SFR], ["pf"])
                so = pf[:, 256:320]
                P.mm(so, sf["Ktok"][:, u, :], Vb[:, u, :], True, False, [F0("Ktok"), Vbn], ["pf"])
                P.mm(so, sf["XW"][:, u, 128:256], sf["NVb"][:, u, :], False, False, [F0("XW"), F0("NVb")], ["pf"])
                P.mm(so, sf["GS"][:, u, :], Sf[:, oc, :], False, True, [F0("GS"), SFR], ["pf"])
                P.ts("vector", Sf[:, oc, :], so, gam[0][:, u:u + 1], None, ALU.mult, None, ["pf", f"gam{z}0"], [SFR])
                yield
            P.cp("vector", YPs[:].rearrange("p u x -> p (u x)"), pf[:, 0:256], ["pf"], ["YPs"])
            P.dma("sync", S["yp"][tg, oc], YPs[:].rearrange("p u x -> p (u x)"), reads=["YPs"], writes=[("yp", tg, oc)], sem="YPs")
            j = nxt("pb")
            for u in range(4):
                so = pb[j][:, u * 64:(u + 1) * 64]
                P.mm(so, sb_["Ktok"][:, u, :], Vb[:, u, :], True, False, [F1("Ktok"), Vbn], [f"pb{j}"])
                P.mm(so, sb_["XW"][:, u, 128:256], sb_["NVb"][:, u, :], False, True, [F1("XW"), F1("NVb")], [f"pb{j}"])
            P.cp("scalar", SAs[:].rearrange("p u x -> p (u x)"), pb[j][:, 0:256], [f"pb{j}"], ["SAs"])
            P.dma("sync", S["sadd"][tg, oc], SAs[:].rearrange("p u x -> p (u x)"), reads=["SAs"], writes=[("sadd", tg, oc)], sem="SAs")
            P.dma("sync", S["gyb"][tg, oc], sb_["GY"][:].rearrange("p u x -> p (u x)"), reads=[F1("GY")], writes=[("gyb", tg, oc)], sem=F1("GY"))
            P.dma("sync", S["gsb"][tg, oc], sb_["GS"][:].rearrange("p u x -> p (u x)"), reads=[F1("GS")], writes=[("gsb", tg, oc)], sem=F1("GS"))
            yield

        NT = len(RW_ORDER1)
        NJ = NT * 8
        done = {"prep": set(), "c0": set(), "c1": set(), "fin": set(), "tprep": set()}

        def stream_P():
            for ti in range(NT):
                yield ("tprep", ti, lambda ti=ti: (ti == 0 or ("prep", (ti - 1) * 8 + 7) in donef), lambda ti=ti: tprep(ti))
                for oc in range(8):
                    k = ti * 8 + oc
                    yield ("prep", k, lambda k=k: ((k < 2 or (("c0", k - 2) in donef and ("c1", k - 2) in donef)) and (k < 3 or ("fin", k - 3) in donef)),
                           lambda ti=ti, oc=oc, k=k: prep(ti, oc, k % 2, k % 3))

        def stream_C(d):
            for k in range(NJ):
                ti, oc = divmod(k, 8)
                yield (f"c{d}", k, lambda k=k: (("prep", k) in donef and (k < 2 or ("fin", k - 2) in donef)),
                       lambda ti=ti, oc=oc, k=k: chain(ti, oc, k % 2, d, k % 3))

        def stream_F():
            for k in range(NJ):
                ti, oc = divmod(k, 8)
                yield ("fin", k, lambda k=k: (("c0", k) in donef and ("c1", k) in donef),
                       lambda ti=ti, oc=oc, k=k: finish(ti, oc, k % 2, k % 3))

        donef = set()
        load_hh(0)
        streams = [stream_C(0), stream_C(1), stream_F(), stream_P()]
        cur = [None] * 4
        pend = [None] * 4
        alive = [True] * 4
        while any(alive):
            progressed = False
            for si in range(4):
                if not alive[si]:
                    continue
                if cur[si] is None:
                    if pend[si] is None:
                        try:
                            pend[si] = next(streams[si])
                        except StopIteration:
                            alive[si] = False
                            continue
                    kind, k, ready, mk = pend[si]
                    if not ready():
                        continue
                    cur[si] = (kind, k, mk())
                    pend[si] = None
                kind, k, gen = cur[si]
                try:
                    next(gen)
                    progressed = True
                except StopIteration:
                    donef.add((kind, k))
                    cur[si] = None
                    progressed = True
            assert progressed or not any(alive), "scheduler stuck"


def stage_rwkv1a(P, io, G, hp, S):
    vec, masks, identb, identf, bones, onesb, rmask = (G[k] for k in ("vec", "masks", "identb", "identf", "bones", "onesb", "rmask"))
    with P.phase("rwkv1a"):
        wr = P.sb([128, 8, 1024], BF16)
        wk = P.sb([128, 8, 1024], BF16)
        wv = P.sb([128, 8, 1024], BF16)
        for w, nm in ((wr, "rwkv_wr"), (wk, "rwkv_wk"), (wv, "rwkv_wv")):
            P.dma("gpsimd", w[:], fm(io[nm]), writes=[nm], sem=nm)
        lw1 = P.sb([128, 8, 128], BF16)
        la1 = P.sb([128, 8, 128], BF16)
        g1 = P.sb([128, 8, 128], BF16)
        for d in range(2):
            P.dma("gpsimd", lw1[:, :, d * 64:(d + 1) * 64], io["rwkv_w1"][d].rearrange("(c p) j -> p c j", p=128), writes=["lw1"], sem=f"lw1{d}")
            P.dma("gpsimd", la1[:, :, d * 64:(d + 1) * 64], io["rwkv_a1"][d].rearrange("(c p) j -> p c j", p=128), writes=["la1"], sem=f"la1{d}")
        P.dma("gpsimd", g1[:], io["rwkv_g1"].rearrange("(c p) j -> p c j", p=128), writes=["g1"], sem="g1")
        w2s = P.sb([128, 1024], BF16)
        a2s = P.sb([128, 1024], BF16)
        g2 = P.sb([128, 1024], BF16)
        P.dma("gpsimd", w2s[:], io["rwkv_w2"].rearrange("d j f -> (d j) f"), writes=["w2s"], sem="w2s")
        P.dma("gpsimd", a2s[:], io["rwkv_a2"].rearrange("d j f -> (d j) f"), writes=["a2s"], sem="a2s")
        P.dma("gpsimd", g2[:], io["rwkv_g2"], writes=["g2"], sem="g2")

        hh = P.sb([128, 8, 384], F32)
        xx = P.sb([128, 8, 256], F32)
        xr = P.sb([128, 8, 256], BF16)
        xk = P.sb([128, 8, 256], BF16)
        xv = P.sb([128, 8, 256], BF16)
        xrot = P.sb([128, 8, 256], BF16)
        lwt = P.sb([128, 256], BF16)
        lat = P.sb([128, 256], BF16)
        sg = P.sb([128, 256], BF16)
        NSET = 2
        bufs = []
        for w_ in range(NSET):
            B_ = {"t": {}}
            for nm in ("r", "k", "sw0", "sw1", "ag0", "ag1", "kq", "lnv", "rs", "kkn", "fac", "kd0", "kd1", "b0", "b1",
                       "L", "Lx", "Lb", "E1", "E2", "E3", "ks"):
                B_["t"][nm] = P.sb([128, 256], F32, f"t{w_}_" + nm)
            B_["sqb"] = P.sb([128, 256], BF16)
            B_["RK"] = P.sb([128, 4, 2, 64], BF16)
            B_["VTbd"] = P.sb([128, 4, 128], F32)
            B_["GTbd"] = P.sb([128, 4, 128], F32)
            B_["Vf"] = P.sb([128, 4, 64], F32)
            B_["Gf"] = P.sb([128, 4, 64], F32)
            B_["ops"] = P.sb([128, 2, 4, 256], BF16)
            B_["vb"] = P.sb([128, 4, 64], BF16)
            B_["gam"] = P.sb([128, 2, 4], F32)
            bufs.append(B_)
            P.memset("gpsimd", B_["RK"][:], 0.0, [("RK", w_)])
            P.memset("gpsimd", B_["VTbd"][:], 0.0, [("VTbd", w_)])
            P.memset("gpsimd", B_["GTbd"][:], 0.0, [("GTbd", w_)])
        P.ns_set = frozenset(["r", "k", "sw0", "sw1", "ag0", "ag1", "kq", "lnv", "rs", "kkn", "fac", "kd0", "kd1", "b0", "b1",
                              "L", "Lx", "Lb", "E1", "E2", "E3", "ks", "sqb", "RK", "VTbd", "GTbd", "Vf", "Gf", "ops_st", "vb_st", "gam_st"])
        gamb_t = P.sb([128, 8, 4], F32)
        bon_t = P.sb([128, 8, 4], F32)
        ppt = [P.ps([128, 512], F32) for _ in range(4)]
        pp = [t_[:, 0:256] for t_ in ppt]
        pb = [P.ps([128, 512], F32) for _ in range(4)]
        cnt = {"pp": 0, "pb": 0}
        nmod = {"pp": 4, "pb": 4}

        def nxt(kind):
            i = cnt[kind] % nmod[kind]
            cnt[kind] += 1
            return i

        def v3(ap):
            return ap.rearrange("p (u s) -> p u s", s=64)

        def u128(ap):
            return ap.rearrange("p (u x) -> p u x", x=128)

        def load_hh(ti):
            isctx, idx = RW_ORDER1[ti]
            off = 4288 if isctx else 64 + 256 * idx
            P.dma("sync", hh[:], fm(hp[:, off - 64: off + 320]), writes=["hh"], sem="hh")

        def proj8(w_cols_fn, xb, bn, extra_r):
            i = nxt("pp")
            for c in range(8):
                P.mm(pp[i], w_cols_fn(c), xb[:, c, :], c == 0, c == 7, [(bn, c)] + extra_r, [f"pp{i}"])
            return i

        def tprep(ti):
            isctx, idx = RW_ORDER1[ti]
            hc = hh[:, :, 64:320]
            XXW = [("xx", c) for c in range(8)]
            if not isctx:
                h4 = hh[:, :, 64:320].rearrange("p c (r w) -> p c r w", w=64)
                x4 = xx[:].rearrange("p c (r w) -> p c r w", w=64)
                P.tt("vector", x4[:, 0:2, :, 1:64], h4[:, 0:2, :, 0:63], h4[:, 0:2, :, 1:64], ALU.subtract, ["hh"], XXW[0:2])
                P.ts("gpsimd", x4[:, 0:2, :, 0:1], h4[:, 0:2, :, 0:1], -1.0, 0.0, ALU.mult, ALU.add, ["hh"], [("xxe", 0)])
                P.tt("vector", x4[:, 2:4, :, 0:63], h4[:, 2:4, :, 1:64], h4[:, 2:4, :, 0:63], ALU.subtract, ["hh"], XXW[2:4])
                P.ts("gpsimd", x4[:, 2:4, :, 63:64], h4[:, 2:4, :, 63:64], -1.0, 0.0, ALU.mult, ALU.add, ["hh"], [("xxe", 1)])
                P.tt("gpsimd", xx[:, 4:6, :], hh[:, 4:6, 0:256], hh[:, 4:6, 64:320], ALU.subtract, ["hh"], XXW[4:6])
                P.tt("gpsimd", xx[:, 6:8, :], hh[:, 6:8, 128:384], hh[:, 6:8, 64:320], ALU.subtract, ["hh"], XXW[6:8])
            else:
                P.tt("vector", xx[:, 0:4, :], hh[:, 0:4, 63:319], hh[:, 0:4, 64:320], ALU.subtract, ["hh"], XXW[0:4] + [("xxe", 0)])
                P.tt("gpsimd", xx[:, 4:8, :], hh[:, 4:8, 65:321], hh[:, 4:8, 64:320], ALU.subtract, ["hh"], XXW[4:8] + [("xxe", 1)])
            yield

            def mk_xj(j, buf, bn):
                for c in range(8):
                    P.stt(buf[:, c, :], xx[:, c, :], vec[:, 5 + j, c:c + 1], hc[:, c, :], ALU.mult, ALU.add,
                          [("xx", c), ("xxe", 0), ("xxe", 1), "hh", "vec"], [(bn, c)])

            mk_xj(1, xrot, "xrot")
            yield
            i = proj8(lambda c: lw1[:, c, :], xrot, "xrot", ["lw1"])
            P.act(lwt[:], pp[i], AF.Tanh, [f"pp{i}"], ["lwt"])
            yield
            mk_xj(4, xrot, "xrot")
            yield
            i = proj8(lambda c: la1[:, c, :], xrot, "xrot", ["la1"])
            P.cp("scalar", lat[:], pp[i], [f"pp{i}"], ["lat"])
            yield
            mk_xj(5, xrot, "xrot")
            yield
            i = proj8(lambda c: g1[:, c, :], xrot, "xrot", ["g1"])
            P.act(sg[:], pp[i], AF.Sigmoid, [f"pp{i}"], ["sg"])
            yield
            mk_xj(0, xr, "xr")
            yield
            mk_xj(2, xk, "xk")
            yield
            mk_xj(3, xv, "xv")
            if ti + 1 < len(RW_ORDER1):
                load_hh(ti + 1)
            yield

        def prep(ti, oc, w):
            isctx, idx = RW_ORDER1[ti]
            tg = 16 if isctx else idx
            cs = slice(oc * 128, (oc + 1) * 128)
            B_ = bufs[w]
            t, sqb, RK, VTbd, GTbd, Vf, Gf = B_["t"], B_["sqb"], B_["RK"], B_["VTbd"], B_["GTbd"], B_["Vf"], B_["Gf"]
            ops_st, vb_st, gam_st = B_["ops"], B_["vb"], B_["gam"]
            i = proj8(lambda c: wr[:, c, cs], xr, "xr", ["rwkv_wr"])
            P.cp("scalar", t["r"][:], pp[i], [f"pp{i}"], ["r"])
            i = proj8(lambda c: wk[:, c, cs], xk, "xk", ["rwkv_wk"])
            P.cp("scalar", t["k"][:], pp[i], [f"pp{i}"], ["k"])
            i = proj8(lambda c: wv[:, c, cs], xv, "xv", ["rwkv_wv"])
            vt4 = VTbd[:].rearrange("p u (h s) -> p u h s", h=2)
            for h2 in range(2):
                sl = slice(h2 * 64, (h2 + 1) * 64)
                P.cp("scalar", vt4[sl, :, h2, :], v3(pp[i][sl, :]), [f"pp{i}"], ["VTbd"])
            i = nxt("pp")
            P.mm(pp[i], g2[:, cs], sg[:], True, True, ["g2", "sg"], [f"pp{i}"])
            gt4 = GTbd[:].rearrange("p u (h s) -> p u h s", h=2)
            for h2 in range(2):
                sl = slice(h2 * 64, (h2 + 1) * 64)
                P.cp("scalar", gt4[sl, :, h2, :], v3(pp[i][sl, :]), [f"pp{i}"], ["GTbd"])
            yield
            j = nxt("pb")
            for u in range(4):
                P.tr(pb[j][:, u * 128:(u + 1) * 128], VTbd[:, u, :], identf[:], ["VTbd", "identf"], [f"pb{j}"])
            pv = u128(pb[j][:])
            for h2 in range(2):
                sl = slice(h2 * 64, (h2 + 1) * 64)
                P.cp("scalar", Vf[sl, :, :], pv[sl, :, h2 * 64:(h2 + 1) * 64], [f"pb{j}"], ["Vf"])
            P.cp("gpsimd", vb_st[:], Vf[:], ["Vf"], ["vb_st"])
            P.dma("sync", S["vst"][tg, oc].rearrange("p (u s) -> p u s", s=64), Vf[:], reads=["Vf"], writes=[("vst", tg, oc)], sem="Vf")
            j = nxt("pb")
            for u in range(4):
                P.tr(pb[j][:, u * 128:(u + 1) * 128], GTbd[:, u, :], identf[:], ["GTbd", "identf"], [f"pb{j}"])
            pv = u128(pb[j][:])
            for h2 in range(2):
                sl = slice(h2 * 64, (h2 + 1) * 64)
                P.cp("scalar", Gf[sl, :, :], pv[sl, :, h2 * 64:(h2 + 1) * 64], [f"pb{j}"], ["Gf"])
            P.dma("sync", S["gst"][tg, oc].rearrange("p (u s) -> p u s", s=64), Gf[:], reads=["Gf"], writes=[("gst", tg, oc)], sem="Gf")
            yield
            for d in range(2):
                dl = slice(d * 64, (d + 1) * 64)
                i = nxt("pp")
                P.mm(pp[i], w2s[dl, cs], lwt[dl, :], True, True, ["w2s", "lwt"], [f"pp{i}"])
                P.act(t[f"sw{d}"][:], pp[i], AF.Sigmoid, [f"pp{i}", "vec"], [f"sw{d}"], bias=vec[:, 11 + d, oc:oc + 1])
                i = nxt("pp")
                P.mm(pp[i], a2s[dl, cs], lat[dl, :], True, True, ["a2s", "lat"], [f"pp{i}"])
                P.act(t[f"ag{d}"][:], pp[i], AF.Sigmoid, [f"pp{i}", "vec"], [f"ag{d}"], bias=vec[:, 13 + d, oc:oc + 1])
            yield
            P.ts("vector", t["kq"][:], t["k"][:], vec[:, 15, oc:oc + 1], None, ALU.mult, None, ["k", "vec"], ["kq"])
            P.act(sqb[:], t["kq"][:], AF.Square, ["kq"], ["sqb"])
            i = nxt("pp")
            P.mm(pp[i], bones[:], sqb[:], True, True, ["bones", "sqb"], [f"pp{i}"])
            P.act(t["lnv"][:], pp[i], AF.Ln, [f"pp{i}"], ["lnv"], bias=1e-12)
            P.act(t["rs"][:], t["lnv"][:], AF.Exp, ["lnv"], ["rs"], scale=-0.5)
            P.tt("gpsimd", t["kkn"][:], t["kq"][:], t["rs"][:], ALU.mult, ["kq", "rs"], ["kkn"])
            for d in range(2):
                sw, ag, kd, bb = t[f"sw{d}"], t[f"ag{d}"], t[f"kd{d}"], t[f"b{d}"]
                EE = "gpsimd" if d == 0 else "vector"
                P.ts(EE, t["fac"][:], ag[:], vec[:, 16, oc:oc + 1], vec[:, 17, oc:oc + 1], ALU.mult, ALU.add, [f"ag{d}", "vec"], ["fac"])
                P.tt(EE, kd[:], t["k"][:], t["fac"][:], ALU.mult, ["k", "fac"], [f"kd{d}"])
                P.tt(EE, bb[:], t["kkn"][:], ag[:], ALU.mult, ["kkn", f"ag{d}"], [f"b{d}"])
                P.op("vector", lambda e, sw=sw: e.tensor_tensor_scan(out=t["L"][:], data0=rmask[:], data1=sw[:], initial=0.0,
                                                                      op0=ALU.mult, op1=ALU.add), [f"sw{d}", "rmask"], ["L"])
                L3 = v3(t["L"][:])
                if d == 0:
                    P.tt(EE, t["Lx"][:], t["L"][:], sw[:], ALU.subtract, ["L", f"sw{d}"], ["Lx"])
                    Li, Lin = t["L"], "L"
                else:
                    P.tt(EE, v3(t["Lx"][:]), L3[:, :, 63:64].broadcast_to([128, 4, 64]), L3, ALU.subtract, ["L"], ["Lx"])
                    P.tt(EE, t["Lb"][:], t["Lx"][:], sw[:], ALU.add, ["Lx", f"sw{d}"], ["Lb"])
                    Li, Lin = t["Lb"], "Lb"
                P.act(t["E1"][:], Li[:], AF.Exp, [Lin], ["E1"], scale=-C0)
                P.act(t["E3"][:], Li[:], AF.Exp, [Lin], ["E3"], scale=C0)
                P.act(t["E2"][:], t["Lx"][:], AF.Exp, ["Lx"], ["E2"], scale=-C0)
                P.stt(ops_st[:, d, 0, :], t["kkn"][:], -1.0, t["E2"][:], ALU.mult, ALU.mult, ["kkn", "E2"], ["ops_st"])
                P.tt("gpsimd", ops_st[:, d, 1, :], t["r"][:], t["E1"][:], ALU.mult, ["r", "E1"], ["ops_st"])
                P.tt(EE, ops_st[:, d, 2, :], kd[:], t["E3"][:], ALU.mult, [f"kd{d}", "E3"], ["ops_st"])
                P.tt(EE, ops_st[:, d, 3, :], bb[:], t["E3"][:], ALU.mult, [f"b{d}", "E3"], ["ops_st"])
                E13 = v3(t["E1"][:])
                gsrc = E13[:, :, 63] if d == 0 else E13[:, :, 0]
                P.cp("vector", gam_st[:, d, :], gsrc, ["E1"], ["gam_st"])
                if d == 1:
                    P.cp("gpsimd", gamb_t[:, oc, :], gam_st[:, 1, :], ["gam_st"], ["gamb_t"])
                yield
            P.tt("gpsimd", t["ks"][:], t["kd0"][:], t["kd1"][:], ALU.add, ["kd0", "kd1"], ["ks"])
            for h2 in range(2):
                sl = slice(h2 * 64, (h2 + 1) * 64)
                P.stt(RK[sl, :, h2, :], v3(t["r"][sl, :]), vec[sl, 18, oc:oc + 1], v3(t["ks"][sl, :]), ALU.mult, ALU.mult, ["r", "ks", "vec"], ["RK"])
            i = nxt("pp")
            for u in range(4):
                P.mm(pp[i][:, u:u + 1], RK[:, u, :, :].rearrange("p h s -> p (h s)"), onesb[:, 0:1], True, True, ["RK", "onesb"], [f"pp{i}"])
            P.cp("scalar", bon_t[:, oc, :], pp[i][:, 0:4], [f"pp{i}"], ["bon_t"])
            P.dma("sync", S["ops"][tg, oc], ops_st[:].rearrange("p d x n -> p (d x n)"), reads=["ops_st"], writes=[("ops", tg, oc)], sem="ops_st")
            P.dma("sync", S["vb"][tg, oc], vb_st[:].rearrange("p u s -> p (u s)"), reads=["vb_st"], writes=[("vb", tg, oc)], sem="vb_st")
            P.dma("sync", S["gam"][tg, oc], gam_st[:].rearrange("p d u -> p (d u)"), reads=["gam_st"], writes=[("gam", tg, oc)], sem="gam_st")
            yield


        NT = len(RW_ORDER1)
        load_hh(0)
        for ti in range(NT):
            isctx, idx = RW_ORDER1[ti]
            tg = 16 if isctx else idx
            for _ in tprep(ti):
                pass
            jobs = [(oc % NSET, prep(ti, oc, oc % NSET)) for oc in range(8)]
            active = []
            while jobs or active:
                while jobs and len(active) < NSET:
                    active.append(jobs.pop(0))
                for item in list(active):
                    P.ns = item[0]
                    try:
                        next(item[1])
                    except StopIteration:
                        active.remove(item)
                    P.ns = None
            P.dma("sync", S["gamb"][tg], gamb_t[:].rearrange("p a b -> p (a b)"), reads=["gamb_t"], writes=[("gamb", tg)], sem="gamb_t")
            P.dma("sync", S["bon"][tg], bon_t[:].rearrange("p a b -> p (a b)"), reads=["bon_t"], writes=[("bon", tg)], sem="bon_t")
        P.ns_set = frozenset()


def stage_rwkv1b(P, io, G, S):
    vec, masks, identb, identf, bones, onesb, rmask = (G[k] for k in ("vec", "masks", "identb", "identf", "bones", "onesb", "rmask"))
    with P.phase("rwkv1b"):
        YPs = P.sb([128, 4, 64], F32)
        SAs = P.sb([128, 4, 64], F32)
        Sf = P.sb([128, 8, 64], BF16)
        ARq = [[P.sb([128, 4, 2, 128], BF16, f"AR{q}{d}") for d in range(2)] for q in range(3)]
        KTq = [[P.sb([128, 4, 128], BF16, f"KT{q}{d}") for d in range(2)] for q in range(3)]
        BTq = [[P.sb([128, 4, 128], BF16, f"BT{q}{d}") for d in range(2)] for q in range(3)]
        stg = [P.sb([128, 2, 4, 256], BF16, f"stg{q}") for q in range(3)]
        Vbq = [P.sb([128, 4, 64], BF16, f"Vb{q}") for q in range(4)]
        gamq = [P.sb([128, 2, 4], F32, f"gam{q}") for q in range(4)]
        inv = []
        for d in range(2):
            st = {}
            for nm, shp in (("Atok", [128, 4, 128]), ("Btok", [128, 4, 128]), ("MQ", [128, 4, 256]), ("MWa", [128, 4, 2, 128]),
                            ("MWb", [128, 4, 2, 128]), ("MTa", [128, 4, 128]), ("MTb", [128, 4, 128])):
                st[nm] = P.sb(shp, BF16, f"i{d}_{nm}")
            inv.append(st)
        fin = []
        for q in range(2):
            row = []
            for d in range(2):
                st = {}
                for nm, shp in (("Ktok", [128, 4, 128]), ("NP", [128, 4, 256]), ("XW", [128, 4, 256]), ("NVb", [128, 4, 64]),
                                ("GY", [128, 4, 128]), ("GS", [128, 4, 128])):
                    st[nm] = P.sb(shp, BF16, f"f{q}{d}_{nm}")
                row.append(st)
            fin.append(row)
        pf = P.ps([128, 512], F32)
        pb = [P.ps([128, 512], F32) for _ in range(7)]
        cnt = {"pb": 0}
        nmod = {"pb": 7}

        def nxt(kind):
            i = cnt[kind] % nmod[kind]
            cnt[kind] += 1
            return i

        for q in range(3):
            for d in range(2):
                P.memset("gpsimd", ARq[q][d][:], 0.0, [f"AR{q}{d}"])
                P.memset("gpsimd", KTq[q][d][:], 0.0, [f"KT{q}{d}"])
                P.memset("gpsimd", BTq[q][d][:], 0.0, [f"BT{q}{d}"])
        P.memset("gpsimd", Sf[:], 0.0, [("Sf", p) for p in range(8)])

        def v3(ap):
            return ap.rearrange("p (u s) -> p u s", s=64)

        def u128(ap):
            return ap.rearrange("p (u x) -> p u x", x=128)

        def loadjob(ti, oc, a, z):
            isctx, idx = RW_ORDER1[ti]
            tg = 16 if isctx else idx
            sg_ = stg[a]
            P.dma("sync", sg_[:].rearrange("p d x n -> p (d x n)"), S["ops"][tg, oc], writes=[f"stg{a}"], sem=f"stg{a}")
            P.dma("sync", Vbq[z][:].rearrange("p u s -> p (u s)"), S["vb"][tg, oc], writes=[f"Vb{z}"], sem=f"Vb{z}")
            P.dma("sync", gamq[z][:].rearrange("p d u -> p (d u)"), S["gam"][tg, oc], writes=[f"gam{z}"], sem=f"gam{z}")
            yield
            for d in range(2):
                ar5 = ARq[a][d][:].rearrange("p u a (h s) -> p u a h s", h=2)
                kt4 = KTq[a][d][:].rearrange("p u (h s) -> p u h s", h=2)
                bt4 = BTq[a][d][:].rearrange("p u (h s) -> p u h s", h=2)
                for h2 in range(2):
                    sl = slice(h2 * 64, (h2 + 1) * 64)
                    P.cp("gpsimd", ar5[sl, :, 0, h2, :], v3(sg_[sl, d, 0, :]), [f"stg{a}"], [f"AR{a}{d}"])
                    P.cp("gpsimd", ar5[sl, :, 1, h2, :], v3(sg_[sl, d, 1, :]), [f"stg{a}"], [f"AR{a}{d}"])
                    P.cp("gpsimd", kt4[sl, :, h2, :], v3(sg_[sl, d, 2, :]), [f"stg{a}"], [f"KT{a}{d}"])
                    P.cp("gpsimd", bt4[sl, :, h2, :], v3(sg_[sl, d, 3, :]), [f"stg{a}"], [f"BT{a}{d}"])
                    yield

        def chain(ti, oc, q, d, z, a):
            AR, KT, BT, Vb = ARq[a][d], KTq[a][d], BTq[a][d], Vbq[z]
            ARn, KTn, BTn, Vbn = f"AR{a}{d}", f"KT{a}{d}", f"BT{a}{d}", f"Vb{z}"
            iv, fn = inv[d], fin[q][d]
            IR = lambda nm: f"i{d}_{nm}"
            FR = lambda nm: f"f{q}{d}_{nm}"
            mS, mC = (0, 2) if d == 0 else (2, 0)
            mSI = masks[:, mS:mS + 2, :].rearrange("p a b -> p (a b)").unsqueeze(1).broadcast_to([128, 4, 256])
            mCb = masks[:, mC, :].unsqueeze(1).broadcast_to([128, 4, 128])
            idb = identb[:].unsqueeze(1).broadcast_to([128, 4, 128])
            for src, srcn, dst, dstn in ((AR[:, :, 0, :], ARn, iv["Atok"], IR("Atok")), (BT[:], BTn, iv["Btok"], IR("Btok")),
                                         (KT[:], KTn, fn["Ktok"], FR("Ktok"))):
                j = nxt("pb")
                pbt = pb[j][:].bitcast(BF16)
                for u in range(4):
                    P.tr(pbt[:, u * 128:(u + 1) * 128], src[:, u, :], identb[:], [srcn, "identb"], [f"pb{j}"])
                P.cp("scalar", dst[:].rearrange("p u x -> p (u x)"), pbt[:, 0:512], [f"pb{j}"], [dstn])
            mSb = masks[:, mS, :].unsqueeze(1).broadcast_to([128, 4, 128])
            mIb = masks[:, mS + 1, :].unsqueeze(1).broadcast_to([128, 4, 128])

            def two_bank(mm_fn):
                j0, j1 = nxt("pb"), nxt("pb")
                for u in range(4):
                    mm_fn(u, pb[j0][:, u * 128:(u + 1) * 128], f"pb{j0}", pb[j1][:, u * 128:(u + 1) * 128], f"pb{j1}")
                return j0, j1

            for lhs, lhsn, dst, dstn in ((BT, BTn, iv["MQ"], IR("MQ")), (KT, KTn, fn["NP"], FR("NP"))):
                def mm_ab(u, o0, n0, o1, n1, lhs=lhs, lhsn=lhsn):
                    P.mm(o0, lhs[:, u, :], AR[:, u, 0, :], True, True, [lhsn, ARn], [n0])
                    P.mm(o1, lhs[:, u, :], AR[:, u, 1, :], True, True, [lhsn, ARn], [n1])
                j0, j1 = two_bank(mm_ab)
                P.tt("vector", dst[:, :, 0:128], u128(pb[j0][:]), mSb, ALU.mult, [f"pb{j0}", "masks"], [dstn])
                P.tt("vector", dst[:, :, 128:256], u128(pb[j1][:]), mIb, ALU.mult, [f"pb{j1}", "masks"], [dstn])
            j = nxt("pb")
            for u in range(4):
                P.mm(pb[j][:, u * 128:(u + 1) * 128], AR[:, u, 0, :], BT[:, u, :], True, True, [ARn, BTn], [f"pb{j}"])
            cur, curn, nx, nxn = iv["MWa"], IR("MWa"), iv["MWb"], IR("MWb")
            P.tt("vector", cur[:, :, 0, :], u128(pb[j][:]), mCb, ALU.mult, [f"pb{j}", "masks"], [curn])
            yield
            j = nxt("pb")
            for u in range(4):
                P.mm(pb[j][:, u * 128:(u + 1) * 128], iv["MQ"][:, u, 0:128], cur[:, u, 0, :], True, True, [IR("MQ"), curn], [f"pb{j}"])
            P.cp("scalar", nx[:, :, 0, :], u128(pb[j][:]), [f"pb{j}"], [nxn])
            P.tt("gpsimd", nx[:, :, 1, :], cur[:, :, 0, :], idb, ALU.add, [curn, "identb"], [nxn])
            j = nxt("pb")
            for u in range(4):
                P.mm(pb[j][:, u * 128:(u + 1) * 128], cur[:, u, 0, :], iv["MQ"][:, u, 0:128], True, True, [IR("MQ"), curn], [f"pb{j}"])
            curT, curTn, nxT, nxTn = iv["MTa"], IR("MTa"), iv["MTb"], IR("MTb")
            P.cp("scalar", curT[:], u128(pb[j][:]), [f"pb{j}"], [curTn])
            cur, curn, nx, nxn = nx, nxn, cur, curn
            yield
            for lev in range(1, 5):
                def mm_lev(u, o0, n0, o1, n1, cur=cur, curn=curn, curT=curT, curTn=curTn):
                    P.mm(o0, curT[:, u, :], cur[:, u, 0, :], True, True, [curTn, curn], [n0])
                    P.mm(o1, curT[:, u, :], cur[:, u, 1, :], True, True, [curTn, curn], [n1])
                j0, j1 = two_bank(mm_lev)
                P.cp("scalar", nx[:, :, 0, :], u128(pb[j0][:]), [f"pb{j0}"], [nxn])
                P.tt("vector", nx[:, :, 1, :], u128(pb[j1][:]), cur[:, :, 1, :], ALU.add, [f"pb{j1}", curn], [nxn])
                j = nxt("pb")
                for u in range(4):
                    P.mm(pb[j][:, u * 128:(u + 1) * 128], cur[:, u, 0, :], curT[:, u, :], True, True, [curn, curTn], [f"pb{j}"])
                P.cp("scalar", nxT[:], u128(pb[j][:]), [f"pb{j}"], [nxTn])
                cur, curn, nx, nxn = nx, nxn, cur, curn
                curT, curTn, nxT, nxTn = nxT, nxTn, curT, curTn
                yield
            j = nxt("pb")
            for u in range(4):
                P.mm(pb[j][:, u * 128:(u + 1) * 128], curT[:, u, :], cur[:, u, 1, :], True, True, [curTn, curn], [f"pb{j}"])
            P.tt("vector", nx[:, :, 1, :], u128(pb[j][:]), cur[:, :, 1, :], ALU.add, [f"pb{j}", curn], [nxn])
            W6, W6n = nx, nxn
            j = nxt("pb")
            for u in range(4):
                P.mm(pb[j][:, u * 64:(u + 1) * 64], fn["NP"][:, u, 0:128], Vb[:, u, :], True, True, [FR("NP"), Vbn], [f"pb{j}"])
            P.cp("scalar", fn["NVb"][:].rearrange("p u x -> p (u x)"), pb[j][:, 0:256], [f"pb{j}"], [FR("NVb")])
            yield

            def mm_d(u, o0, n0, o1, n1):
                P.mm(o0, W6[:, u, 1, :], iv["MQ"][:, u, 128:256], True, True, [W6n, IR("MQ")], [n0])
                P.mm(o1, W6[:, u, 1, :], iv["Btok"][:, u, :], True, True, [W6n, IR("Btok")], [n1])
            j0, j1 = two_bank(mm_d)
            P.cp("scalar", fn["XW"][:, :, 0:128], u128(pb[j0][:]), [f"pb{j0}"], [FR("XW")])
            P.cp("vector", fn["XW"][:, :, 128:256], u128(pb[j1][:]), [f"pb{j1}"], [FR("XW")])
            yield

            def mm_f(u, o0, n0, o1, n1):
                P.mm(o0, iv["Atok"][:, u, :], fn["XW"][:, u, 0:128], True, True, [IR("Atok"), FR("XW")], [n0])
                P.mm(o1, iv["Atok"][:, u, :], fn["XW"][:, u, 128:256], True, True, [IR("Atok"), FR("XW")], [n1])
            j0, j1 = two_bank(mm_f)
            P.tt("vector", fn["GY"][:], u128(pb[j0][:]), AR[:, :, 1, :], ALU.add, [f"pb{j0}", ARn], [FR("GY")])
            P.tt("vector", fn["GS"][:], u128(pb[j1][:]), idb, ALU.add, [f"pb{j1}", "identb"], [FR("GS")])
            yield

        def finish(ti, oc, q, z):
            isctx, idx = RW_ORDER1[ti]
            tg = 16 if isctx else idx
            sf, sb_ = fin[q]
            F0 = lambda nm: f"f{q}0_{nm}"
            F1 = lambda nm: f"f{q}1_{nm}"
            Vb, Vbn, gamz = Vbq[z], f"Vb{z}", gamq[z]
            SFR = ("Sf", oc)
            for u in range(4):
                yo = pf[:, u * 64:(u + 1) * 64]
                P.mm(yo, sf["NP"][:, u, 128:256], Vb[:, u, :], True, False, [F0("NP"), Vbn], ["pf"])
                P.mm(yo, sf["XW"][:, u, 0:128], sf["NVb"][:, u, :], False, False, [F0("XW"), F0("NVb")], ["pf"])
                P.mm(yo, sb_["NP"][:, u, 128:256], Vb[:, u, :], False, False, [F1("NP"), Vbn], ["pf"])
                P.mm(yo, sb_["XW"][:, u, 0:128], sb_["NVb"][:, u, :], False, False, [F1("XW"), F1("NVb")], ["pf"])
                P.mm(yo, sf["GY"][:, u, :], Sf[:, oc, :], False, True, [F0("GY"), SFR], ["pf"])
                so = pf[:, 256:320]
                P.mm(so, sf["Ktok"][:, u, :], Vb[:, u, :], True, False, [F0("Ktok"), Vbn], ["pf"])
                P.mm(so, sf["XW"][:, u, 128:256], sf["NVb"][:, u, :], False, False, [F0("XW"), F0("NVb")], ["pf"])
                P.mm(so, sf["GS"][:, u, :], Sf[:, oc, :], False, True, [F0("GS"), SFR], ["pf"])
                P.ts("vector", Sf[:, oc, :], so, gamz[:, 0, u:u + 1], None, ALU.mult, None, ["pf", f"gam{z}"], [SFR])
                yield
            P.cp("vector", YPs[:].rearrange("p u x -> p (u x)"), pf[:, 0:256], ["pf"], ["YPs"])
            P.dma("sync", S["yp"][tg, oc], YPs[:].rearrange("p u x -> p (u x)"), reads=["YPs"], writes=[("yp", tg, oc)], sem="YPs")
            j = nxt("pb")
            for u in range(4):
                so = pb[j][:, u * 64:(u + 1) * 64]
                P.mm(so, sb_["Ktok"][:, u, :], Vb[:, u, :], True, False, [F1("Ktok"), Vbn], [f"pb{j}"])
                P.mm(so, sb_["XW"][:, u, 128:256], sb_["NVb"][:, u, :], False, True, [F1("XW"), F1("NVb")], [f"pb{j}"])
            P.cp("scalar", SAs[:].rearrange("p u x -> p (u x)"), pb[j][:, 0:256], [f"pb{j}"], ["SAs"])
            P.dma("sync", S["sadd"][tg, oc], SAs[:].rearrange("p u x -> p (u x)"), reads=["SAs"], writes=[("sadd", tg, oc)], sem="SAs")
            P.dma("sync", S["gyb"][tg, oc], sb_["GY"][:].rearrange("p u x -> p (u x)"), reads=[F1("GY")], writes=[("gyb", tg, oc)], sem=F1("GY"))
            P.dma("sync", S["gsb"][tg, oc], sb_["GS"][:].rearrange("p u x -> p (u x)"), reads=[F1("GS")], writes=[("gsb", tg, oc)], sem=F1("GS"))
            yield


        NT = len(RW_ORDER1)
        NJ = NT * 8
        donef = set()

        def stream_L():
            for k in range(NJ):
                ti, oc = divmod(k, 8)
                yield ("load", k, lambda k=k: ((k < 3 or (("c0", k - 3) in donef and ("c1", k - 3) in donef)) and (k < 4 or ("fin", k - 4) in donef)),
                       lambda ti=ti, oc=oc, k=k: loadjob(ti, oc, k % 3, k % 4))

        def stream_C(d):
            for k in range(NJ):
                ti, oc = divmod(k, 8)
                yield (f"c{d}", k, lambda k=k: (("load", k) in donef and (k < 2 or ("fin", k - 2) in donef)),
                       lambda ti=ti, oc=oc, k=k: chain(ti, oc, k % 2, d, k % 4, k % 3))

        def stream_F():
            for k in range(NJ):
                ti, oc = divmod(k, 8)
                yield ("fin", k, lambda k=k: (("c0", k) in donef and ("c1", k) in donef),
                       lambda ti=ti, oc=oc, k=k: finish(ti, oc, k % 2, k % 4))

        streams = [stream_L(), stream_C(0), stream_C(1), stream_F()]
        cur = [None] * 4
        pend = [None] * 4
        alive = [True] * 4
        while any(alive):
            progressed = False
            for si in range(4):
                if not alive[si]:
                    continue
                if cur[si] is None:
                    if pend[si] is None:
                        try:
                            pend[si] = next(streams[si])
                        except StopIteration:
                            alive[si] = False
                            continue
                    kind, k, ready, mk = pend[si]
                    if not ready():
                        continue
                    cur[si] = (kind, k, mk())
                    pend[si] = None
                kind, k, gen = cur[si]
                try:
                    next(gen)
                    progressed = True
                except StopIteration:
                    donef.add((kind, k))
                    cur[si] = None
                    progressed = True
            assert progressed or not any(alive), "scheduler stuck"


def stage_rwkv2(P, io, G, S, src, xa):
    vec, identb = G["vec"], G["identb"]
    GN_EPS = 64e-5
    with P.phase("rwkv2"):
        wo = P.sb([64, 16, 1024], BF16)
        P.dma("gpsimd", wo[:], io["rwkv_wo"].rearrange("(h v) f -> v h f", v=64), writes=["wo"], sem="wo")
        lnw = P.sb([128, 8, 64], F32)
        lnb = P.sb([128, 8, 64], F32)
        P.dma("sync", lnw[:], io["lnw_st"], writes=["lnw"], sem="lnw")
        P.dma("sync", lnb[:], io["lnb_st"], writes=["lnb"], sem="lnb")
        big = {}
        for nm in ("yp", "sadd", "vst", "gst"):
            big[nm] = [P.sb([128, 8, 256], F32, f"l_{nm}{b}") for b in range(2)]
        for nm in ("gyb", "gsb"):
            big[nm] = [P.sb([128, 8, 512], BF16, f"l_{nm}{b}") for b in range(2)]
        gamb = [P.sb([128, 8, 4], F32) for _ in range(2)]
        bon = [P.sb([128, 8, 4], F32) for _ in range(2)]
        xt = [P.sb([128, 8, 256], F32) for _ in range(2)]
        Sb = P.sb([128, 8, 64], BF16)
        ysb2 = [P.sb([128, 8, 64], F32) for _ in range(2)]
        ysq2 = [P.sb([128, 8, 64], F32) for _ in range(2)]
        tmpS = P.sb([128, 8, 64], F32)
        yn2 = [P.sb([128, 8, 64], F32) for _ in range(2)]
        bv2 = [P.sb([128, 8, 64], F32) for _ in range(2)]
        ob2 = [P.sb([128, 8, 64], BF16) for _ in range(2)]
        st2 = [{nm: P.sb([128, 8], F32, f"g{k_}_" + nm) for nm in ("s1", "s2", "mean", "msq", "var", "lnv", "rstd")} for k_ in range(2)]
        OT = P.sb([64, 16, 256], BF16)
        py = [P.ps([128, 512], F32) for _ in range(2)]
        pS = P.ps([128, 512], F32)
        ptr = P.ps([128, 1024], F32)
        pw = [P.ps([128, 512], F32) for _ in range(2)]
        P.memset("gpsimd", Sb[:], 0.0, ["Sb"])

        def load(k):
            isctx, idx = RW_ORDER2[k]
            tg = 16 if isctx else idx
            b = k % 2
            for nm in ("yp", "sadd", "vst", "gst", "gyb", "gsb"):
                P.dma("sync", big[nm][b][:], S[nm][tg].rearrange("o p x -> p o x"), writes=[f"{nm}{b}"], sem=f"{nm}{b}")
            P.dma("sync", gamb[b][:].rearrange("p a b -> p (a b)"), S["gamb"][tg], writes=[f"gamb{b}"], sem=f"gamb{b}")
            P.dma("sync", bon[b][:].rearrange("p a b -> p (a b)"), S["bon"][tg], writes=[f"bon{b}"], sem=f"bon{b}")
            c0 = T if isctx else idx * 256
            P.dma("sync", xt[b][:], fm(src[:, c0:c0 + 256]), writes=[f"xt{b}"], sem=f"xt{b}")

        load(0)
        for k, (isctx, idx) in enumerate(RW_ORDER2):
            b = k % 2
            if k + 1 < len(RW_ORDER2):
                load(k + 1)
            c0 = T if isctx else idx * 256
            _, _, gates = mod_scalars(G, 0, 0, isctx)
            bc = lambda ap: ap.unsqueeze(2).broadcast_to([128, 8, 64])
            def chain_part(u):
                us = slice(u * 64, (u + 1) * 64)
                q_ = u % 2
                for oc in range(8):
                    P.mm(py[q_][:, oc * 64:(oc + 1) * 64], big["gyb"][b][:, oc, u * 128:(u + 1) * 128], Sb[:, oc, :], True, True, [f"gyb{b}", "Sb"], [f"py{q_}"])
                for oc in range(8):
                    P.mm(pS[:, oc * 64:(oc + 1) * 64], big["gsb"][b][:, oc, u * 128:(u + 1) * 128], Sb[:, oc, :], True, True, [f"gsb{b}", "Sb"], ["pS"])
                pS3 = pS[:].rearrange("p (o v) -> p o v", v=64)
                P.tt("vector", tmpS[:], pS3, big["sadd"][b][:, :, us], ALU.add, ["pS", f"sadd{b}"], ["tmpS"])
                P.tt("vector", Sb[:], tmpS[:], bc(gamb[b][:, :, u]), ALU.mult, ["tmpS", f"gamb{b}"], ["Sb"])

            def read_part(u):
                us = slice(u * 64, (u + 1) * 64)
                q_ = u % 2
                ysb, ysq, yn, bv, ob, st = ysb2[q_], ysq2[q_], yn2[q_], bv2[q_], ob2[q_], st2[q_]
                N = lambda nm: f"{nm}{q_}"
                py3 = py[q_][:].rearrange("p (o v) -> p o v", v=64)
                P.tt("vector", ysb[:], py3, big["yp"][b][:, :, us], ALU.add, [f"py{q_}", f"yp{b}"], [N("ysb")])
                P.tt("gpsimd", bv[:], big["vst"][b][:, :, us], bc(bon[b][:, :, u]), ALU.mult, [f"vst{b}", f"bon{b}"], [N("bv")])
                yield
                P.op("vector", lambda e: e.tensor_reduce(out=st["s1"][:], in_=ysb[:], axis=AX.X, op=ALU.add), [N("ysb")], [N("s1")])
                P.tt("gpsimd", ysq[:], ysb[:], ysb[:], ALU.mult, [N("ysb")], [N("ysq")])
                yield
                P.op("vector", lambda e: e.tensor_reduce(out=st["s2"][:], in_=ysq[:], axis=AX.X, op=ALU.add), [N("ysq")], [N("s2")])
                P.ts("vector", st["mean"][:], st["s1"][:], 1.0 / 64, None, ALU.mult, None, [N("s1")], [N("mean")])
                P.tt("vector", st["msq"][:], st["mean"][:], st["mean"][:], ALU.mult, [N("mean")], [N("msq")])
                P.stt(st["var"][:], st["s2"][:], 1.0 / 64, st["msq"][:], ALU.mult, ALU.subtract, [N("s2"), N("msq")], [N("var")])
                yield
                P.act(st["lnv"][:], st["var"][:], AF.Ln, [N("var")], [N("lnv")], bias=GN_EPS)
                P.act(st["rstd"][:], st["lnv"][:], AF.Exp, [N("lnv")], [N("rstd")], scale=-0.5)
                P.tt("gpsimd", yn[:], ysb[:], bc(st["mean"][:]), ALU.subtract, [N("ysb"), N("mean")], [N("yn")])
                yield
                P.tt("vector", yn[:], yn[:], bc(st["rstd"][:]), ALU.mult, [N("yn"), N("rstd")], [N("yn")])
                yield
                P.tt("gpsimd", yn[:], yn[:], lnw[:], ALU.mult, [N("yn"), "lnw"], [N("yn")])
                yield
                P.tt("vector", yn[:], yn[:], lnb[:], ALU.add, [N("yn"), "lnb"], [N("yn")])
                yield
                P.tt("gpsimd", yn[:], yn[:], bv[:], ALU.add, [N("yn"), N("bv")], [N("yn")])
                yield
                P.tt("vector", ob[:], yn[:], big["gst"][b][:, :, us], ALU.mult, [N("yn"), f"gst{b}"], [N("ob")])
                yield
                ptb = ptr[:].bitcast(BF16)
                for oc in range(8):
                    P.tr(ptb[0:64, oc * 128:(oc + 1) * 128], ob[:, oc, :], identb[:], [N("ob"), "identb"], ["ptr"])
                P.cp("scalar", OT[:, :, us], ptb[0:64, 0:1024].rearrange("p (h t) -> p h t", t=64), ["ptr"], ["OT"])
                yield

            def chain_all():
                for u in range(3, -1, -1):
                    chain_part(u)
                    yield

            jobs = [read_part(u) for u in range(3, -1, -1)]
            cgen = chain_all()
            next(cgen)
            active = []
            started = 0
            while jobs or active:
                while jobs and len(active) < 2:
                    if started >= 1:
                        try:
                            next(cgen)
                        except StopIteration:
                            pass
                    active.append(jobs.pop(0))
                    started += 1
                for gen in list(active):
                    try:
                        next(gen)
                    except StopIteration:
                        active.remove(gen)
            for oc in range(8):
                j = oc % 2
                for h in range(16):
                    P.mm(pw[j][:, 0:256], wo[:, h, oc * 128:(oc + 1) * 128], OT[:, h, :], h == 0, h == 15, ["wo", "OT"], [f"pw{j}"])
                P.stt(xt[b][:, oc, :], pw[j][:, 0:256], gates[oc], xt[b][:, oc, :], ALU.mult, ALU.add, [f"pw{j}", f"xt{b}", "modv"], [f"xt{b}"])
            P.dma("sync", fm(xa[:, c0:c0 + 256]), xt[b][:], reads=[f"xt{b}"], writes=[("xa", k)], sem=f"xt{b}")


def stage_qkv(P, io, G, hb, qtd, Kz, VA):
    vec, bones, perm = G["vec"], G["bones"], G["perm"]
    with P.phase("qkv"):
        wq = P.sb([128, 8, 1024], BF16)
        wkd = P.sb([128, 8, 512], BF16)
        wv = P.sb([128, 8, 256], BF16)
        P.dma("gpsimd", wq[:], fm(io["attn_wq"]), writes=["wq"], sem="wq")
        P.dma("gpsimd", wkd[:], fm(io["attn_wkd"]), writes=["wkd"], sem="wkd")
        P.dma("gpsimd", wv[:], fm(io["attn_wv"]), writes=["wv"], sem="wv")
        ht = [P.sb([128, 8, 512], BF16) for _ in range(2)]
        cs = [P.sb([128, 512], F32) for _ in range(2)]
        sn = [P.sb([128, 512], F32) for _ in range(2)]
        NB = 2
        qf = [P.sb([128, 512], F32) for _ in range(NB)]
        sqb = [P.sb([128, 512], BF16) for _ in range(NB)]
        lnv = [P.sb([128, 512], F32) for _ in range(NB)]
        rstd = [P.sb([128, 512], F32) for _ in range(NB)]
        qh = [P.sb([128, 512], F32) for _ in range(NB)]
        qhb = [P.sb([128, 512], BF16) for _ in range(NB)]
        t1 = [P.sb([128, 512], F32) for _ in range(NB)]
        t2 = [P.sb([128, 512], F32) for _ in range(NB)]
        qst = [P.sb([128, 8, 512], BF16) for _ in range(2)]
        pp = [P.ps([128, 512], F32) for _ in range(6)]
        cnt = [0, 0]

        def nxt():
            cnt[0] += 1
            return cnt[0] % 6

        P.memset("gpsimd", VA[:], 0.0, ["VA0"])
        P.memset("gpsimd", VA[:].rearrange("p k (j x) -> p k j x", x=65)[:, :, 0:5, 64:65], 1.0, ["VA0"])
        P.memset("gpsimd", Kz[0][64:128, :, :], 0.0, ["Kz0z"])
        P.memset("gpsimd", Kz[1][0:64, :, :], 0.0, ["Kz1z"])
        tiles = ALL_TILES

        def load(i):
            c0, tw, isctx = tiles[i]
            b = i % 2
            P.dma("sync", ht[b][:, :, :tw], fm(hb[:, c0:c0 + tw]), writes=[f"ht{b}"], sem=f"ht{b}")
            if not isctx:
                P.dma("sync", cs[b][:, :tw], io["cosT"][:, c0:c0 + tw], writes=[f"cs{b}"], sem=f"cs{b}")
                P.dma("sync", sn[b][:, :tw], io["sinT"][:, c0:c0 + tw], writes=[f"sn{b}"], sem=f"sn{b}")

        def normrope(wcols, nscal, dsts, b, tw, isctx, wname, dres="dstqk"):
            cnt[1] += 1
            n = cnt[1] % NB
            i = nxt()
            for c in range(8):
                P.mm(pp[i][:, :tw], wcols(c), ht[b][:, c, :tw], c == 0, c == 7, [wname, f"ht{b}"], [f"pp{i}"])
            P.cp("scalar", qf[n][:, :tw], pp[i][:, :tw], [f"pp{i}"], [f"qf{n}"])
            P.act(sqb[n][:, :tw], qf[n][:, :tw], AF.Square, [f"qf{n}"], [f"sqb{n}"])
            yield
            i = nxt()
            P.mm(pp[i][:, :tw], bones[:], sqb[n][:, :tw], True, True, ["bones", f"sqb{n}"], [f"pp{i}"])
            P.act(lnv[n][:, :tw], pp[i][:, :tw], AF.Ln, [f"pp{i}"], [f"lnv{n}"], bias=1e-6, scale=1.0 / 64)
            P.act(rstd[n][:, :tw], lnv[n][:, :tw], AF.Exp, [f"lnv{n}"], [f"rstd{n}"], scale=-0.5)
            yield
            P.stt(qh[n][:, :tw], qf[n][:, :tw], nscal, rstd[n][:, :tw], ALU.mult, ALU.mult, [f"qf{n}", f"rstd{n}", "vec"], [f"qh{n}"])
            if isctx:
                for dst, sl in dsts:
                    P.cp("gpsimd", dst, qh[n][sl, :tw], [f"qh{n}"], [dres])
                return
            P.cp("gpsimd", qhb[n][:, :tw], qh[n][:, :tw], [f"qh{n}"], [f"qhb{n}"])
            yield
            i = nxt()
            P.mm(pp[i][:, :tw], perm[:], qhb[n][:, :tw], True, True, ["perm", f"qhb{n}"], [f"pp{i}"])
            P.tt("gpsimd", t1[n][:, :tw], qh[n][:, :tw], cs[b][:, :tw], ALU.mult, [f"qh{n}", f"cs{b}"], [f"t1{n}"])
            P.tt("vector", t2[n][:, :tw], pp[i][:, :tw], sn[b][:, :tw], ALU.mult, [f"pp{i}", f"sn{b}"], [f"t2{n}"])
            yield
            for dst, sl in dsts:
                P.tt("gpsimd", dst, t1[n][sl, :tw], t2[n][sl, :tw], ALU.add, [f"t1{n}", f"t2{n}"], [dres])

        ALLP = slice(0, 128)
        load(0)
        for i, (c0, tw, isctx) in enumerate(tiles):
            b = i % 2
            if i + 1 < len(tiles):
                load(i + 1)
            jobs = []
            if not isctx:
                for oc in range(8):
                    jobs.append(normrope(lambda c, oc=oc: wq[:, c, oc * 128:(oc + 1) * 128], vec[:, 19, oc:oc + 1], [(qst[b][:, oc, :tw], ALLP)], b, tw, False, "wq",
                                         dres=(f"qst{b}", oc)))
            for g in range(4):
                jobs.append(normrope(lambda c, g=g: wkd[:, c, g * 128:(g + 1) * 128], vec[:, 20, 0:1],
                                     [(Kz[0][0:64, g, c0:c0 + tw], slice(0, 64)), (Kz[1][64:128, g, c0:c0 + tw], slice(64, 128))], b, tw, isctx, "wkd"))

            def vjob():
                for sub in range(tw // 128):
                    kt = c0 // 128 + sub
                    j = nxt()
                    for c in range(8):
                        P.mm(pp[j][:, 0:256], ht[b][:, c, sub * 128:(sub + 1) * 128], wv[:, c, :], c == 0, c == 7, ["wv", f"ht{b}"], [f"pp{j}"])
                    P.cp("scalar", VA[:, kt, 65:325].rearrange("p (g x) -> p g x", x=65)[:, :, 0:64],
                         pp[j][:, 0:256].rearrange("p (g d) -> p g d", d=64), [f"pp{j}", "VA0"], [("VA", kt)])
                    yield

            jobs.append(vjob())
            active = []
            while jobs or active:
                while jobs and len(active) < 2:
                    active.append(jobs.pop(0))
                for gen in list(active):
                    try:
                        next(gen)
                    except StopIteration:
                        active.remove(gen)
            if not isctx:
                P.dma("sync", fm(qtd[:, c0:c0 + tw]), qst[b][:, :, :tw], reads=[(f"qst{b}", oc) for oc in range(8)], writes=[("qtd", i)], sem=f"qst{b}")


def stage_attn(P, io, G, qtd, Kz, VA, xa):
    with P.phase("attn"):
        wo = P.sb([128, 8, 1024], BF16)
        P.dma("gpsimd", wo[:], fm(io["attn_wo"]), writes=["wo"], sem="wo")
        sel = P.sb([128, 2, 128], F32)
        P.dma("sync", sel[:], io["c_sel"], writes=["sel"], sem="sel")
        PT = [P.sb([128, 1024], BF16) for _ in range(3)]
        osb = [P.sb([128, 512], F32) for _ in range(2)]
        rb = [P.sb([128, 512], F32) for _ in range(2)]
        xt = P.sb([128, 8, 512], F32)
        QB = [P.sb([128, 8, 512], BF16) for _ in range(2)]
        psS = [P.ps([128, 1024], F32) for _ in range(2)]
        psO = [P.ps([128, 512], F32) for _ in range(2)]
        psB = P.ps([128, 512], F32)
        pX = [P.ps([128, 512], F32) for _ in range(1)]
        _, _, gates = mod_scalars(G, 1, 0, False)
        for k in range(2):
            P.memset("gpsimd", osb[k][:], 0.0, [f"osb{k}"])
        def loadq(qb):
            P.dma("sync", QB[qb % 2][:], fm(qtd[:, qb * 512:(qb + 1) * 512]), writes=[("QT", h, qb) for h in range(16)], sem=f"QB{qb % 2}")

        loadq(0)
        for qb in range(8):
            qsl = slice(qb * 512, (qb + 1) * 512)
            QT = QB[qb % 2]
            if qb + 1 < 8:
                loadq(qb + 1)
            P.dma("sync", xt[:], fm(xa[:, qsl]), writes=["xt"], sem="xt")
            steps = [(h, kp) for h in range(16) for kp in range(17)]

            def S(i):
                h, kp = steps[i]
                g, oc, h2 = h // 4, h // 2, h % 2
                for e_ in range(2):
                    kt = 2 * kp + e_
                    P.mm(psS[i % 2][:, e_ * 512:(e_ + 1) * 512], Kz[h2][:, g, kt * 128:(kt + 1) * 128], QT[:, oc, :], True, True,
                         ["Kz", ("QT", 2 * oc, qb), ("QT", 2 * oc + 1, qb)], [f"psS{i % 2}"])

            def epi_a(h):
                o = h % 2
                P.cp("vector", osb[o][:], psO[o][:], [f"psO{o}"], [f"osb{o}"])

            def epi_b(h):
                oc, h2, o = h // 2, h % 2, h % 2
                hs = slice(h2 * 64, h2 * 64 + 64)
                P.mm(psB[:, :], sel[:, h2, :], osb[o][:], True, True, ["sel", f"osb{o}"], ["psB"])
                P.act(rb[o][hs, :], psB[hs, :], AF.Ln, ["psB"], [f"rb{o}"])
                P.act(rb[o][hs, :], rb[o][hs, :], AF.Exp, [f"rb{o}"], [f"rb{o}"], scale=-1.0)
                P.tt("gpsimd", QT[hs, oc, :], osb[o][hs, :], rb[o][hs, :], ALU.mult, [f"osb{o}", f"rb{o}"], [("QT", h, qb)])

            S(0)
            pend = {}
            for i, (h, kp) in enumerate(steps):
                g, h2, o = h // 4, h % 2, h % 2
                if i + 1 < len(steps):
                    S(i + 1)
                p_ = i % 3
                P.act(PT[p_][:], psS[i % 2][:, :], AF.Exp, [f"psS{i % 2}"], [f"PT{p_}"], scale=0.125)
                v0 = 65 + 65 * g if h2 == 0 else 1 + 65 * g
                for e_ in range(2):
                    kt = 2 * kp + e_
                    P.mm(psO[o][:, :], VA[:, kt, v0:v0 + 128], PT[p_][:, e_ * 512:(e_ + 1) * 512], kt == 0, kt == 33, [f"PT{p_}", "VA"], [f"psO{o}"])
                if kp == 16:
                    epi_a(h)
                    pend[i + 3] = h
                if i in pend:
                    epi_b(pend.pop(i))
            for k in sorted(pend):
                epi_b(pend[k])
            for oc in range(8):
                j = 0
                for c in range(8):
                    P.mm(pX[j][:, :], wo[:, c, oc * 128:(oc + 1) * 128], QT[:, c, :], c == 0, c == 7,
                         ["wo", ("QT", 2 * c, qb), ("QT", 2 * c + 1, qb)], [f"pX{j}"])
                P.stt(xt[:, oc, :], pX[j][:, :], gates[oc], xt[:, oc, :], ALU.mult, ALU.add, [f"pX{j}", "xt", "modv"], ["xt"])
            P.dma("sync", fm(xa[:, qsl]), xt[:], reads=["xt"], writes=[("xa", qb)], sem="xt")


IN_SHAPES = {
    "xin": [D, TT], "cvec": [128, 8, 2], "w_mod": [2, D, 6 * D], "b_mod": [2, 6 * D], "vecs": [128, NV, 8],
    "mlp_w1": [2, D, 4 * D], "mlp_w2": [2, 4 * D, D],
    "rwkv_wr": [D, D], "rwkv_wk": [D, D], "rwkv_wv": [D, D], "rwkv_wo": [D, D],
    "rwkv_w1": [2, D, 64], "rwkv_w2": [2, 64, D], "rwkv_a1": [2, D, 64], "rwkv_a2": [2, 64, D],
    "rwkv_g1": [D, 128], "rwkv_g2": [128, D], "lnw_st": [128, 8, 64], "lnb_st": [128, 8, 64],
    "attn_wq": [D, D], "attn_wkd": [D, 512], "attn_wv": [D, 256], "attn_wo": [D, D],
    "cosT": [128, T], "sinT": [128, T],
    "c_ident": [128, 128], "c_ones": [128, 128], "c_bones": [128, 128], "c_masks": [128, 4, 128],
    "c_perm": [128, 128], "c_rmask": [128, 256], "c_sel": [128, 2, 128],
}


class IO(dict):
    def __init__(self, nc):
        super().__init__()
        self.nc = nc
        self.used = []

    def __missing__(self, k):
        ap = self.nc.dram_tensor(k, IN_SHAPES[k], F32, kind="ExternalInput").ap()
        self[k] = ap
        self.used.append(k)
        return ap

    def scratch(self, name, shape, dtype):
        return self.nc.dram_tensor(name, list(shape), dtype, kind="Internal").ap()

    def output(self, name, shape, dtype=F32):
        return self.nc.dram_tensor(name, list(shape), dtype, kind="ExternalOutput").ap()


def build(stages="all", dbg=None):
    nc = bass.Bass("TRN2", target_bir_lowering=False)
    io = IO(nc)
    P = Prog(nc)
    G = {}
    outs = {}
    stage_init(P, io, G)
    xa = io.scratch("xa", [D, TT], F32)
    hb = io.scratch("hb", [D, TT], BF16)
    if stages == "t_mlp":
        outs["dbg_h"] = io.output("dbg_h", [D, TT], BF16)
        stage_norm(P, io, G, "n_t", io["xin"], ALL_TILES,
                   lambda ic: mod_scalars(G, 0, 1, ic)[0], lambda ic: mod_scalars(G, 0, 1, ic)[1],
                   lambda c0, tw, ic: fm(hb[:, c0:c0 + tw]), BF16)
        with P.phase("copy"):
            P.dma("sync", xa, io["xin"], writes=["xa"], sem="cpa")
            P.dma("sync", outs["dbg_h"], hb, writes=["o"], sem="cpb")
        stage_mlp(P, io, G, 0, ALL_TILES, xa, hb)
        outs["y"] = io.output("y", [D, TT])
        fin = [G["vec"][:, 4, c:c + 1] for c in range(8)]
        stage_norm(P, io, G, "final", xa, ALL_TILES, lambda ic: fin, lambda ic: None,
                   lambda c0, tw, ic: fm(outs["y"][:, c0:c0 + tw]), F32)
    if stages in ("all", "l0", "l1pre"):
        hp = io.scratch("hp", [D, 4608], F32)
        S = rw_scratch(io)
        with P.phase("zpad"):
            z = P.sb([128, 8, 64], F32)
            P.memset("vector", z[:], 0.0, ["z"])
            for k, o in enumerate((0, 64 + T, 4224, 4288 + C)):
                P.dma("sync", fm(hp[:, o:o + 64]), z[:], reads=["z"], writes=[("hpz", k)], sem=f"z{k}")

        def hdst(c0, tw, ic):
            o = 4288 if ic else 64 + c0
            return fm(hp[:, o:o + tw])

        def hbdst(c0, tw, ic):
            return fm(hb[:, c0:c0 + tw])

        def ms(l, kind, which):
            return lambda ic: mod_scalars(G, l, kind, ic)[which]

        stage_norm(P, io, G, "n_mix0", io["xin"], ALL_TILES, ms(0, 0, 0), ms(0, 0, 1), hdst, F32)
        stage_rwkv1a(P, io, G, hp, S)
        stage_rwkv1b(P, io, G, S)
        stage_rwkv2(P, io, G, S, io["xin"], xa)
        stage_norm(P, io, G, "n_mlp0", xa, ALL_TILES, ms(0, 1, 0), ms(0, 1, 1), hbdst, BF16)
        stage_mlp(P, io, G, 0, ALL_TILES, xa, hb)
        if stages == "l0":
            outs["y"] = io.output("y", [D, TT])
            with P.phase("copyout"):
                P.dma("sync", outs["y"], xa, writes=["o"], sem="cpa")
        else:
            stage_norm(P, io, G, "n_mix1", xa, ALL_TILES, ms(1, 0, 0), ms(1, 0, 1), hbdst, BF16)
            with P.scope():
                QT = io.scratch("qtd", [D, T], BF16)
                Kz = [P.ssb([128, 4, TT], BF16, f"Kz{k}") for k in range(2)]
                VA = P.ssb([128, 34, 390], BF16, "VA")
                stage_qkv(P, io, G, hb, QT, Kz, VA)
                stage_attn(P, io, G, QT, Kz, VA, xa)
            if stages == "l1pre":
                outs["y"] = io.output("y", [D, TT])
                with P.phase("copyout"):
                    P.dma("sync", outs["y"], xa, writes=["o"], sem="cpa")
            else:
                stage_norm(P, io, G, "n_mlp1", xa, LAT_TILES, ms(1, 1, 0), ms(1, 1, 1), hbdst, BF16)
                stage_mlp(P, io, G, 1, LAT_TILES, xa, hb)
                outs["y"] = io.output("y", [D, T])
                fin = [G["vec"][:, 4, c:c + 1] for c in range(8)]
                stage_norm(P, io, G, "final", xa, LAT_TILES, lambda ic: fin, lambda ic: None,
                           lambda c0, tw, ic: fm(outs["y"][:, c0:c0 + tw]), F32)
    if stages == "t_rwkv":
        hp = io.scratch("hp", [D, 4608], F32)
        S = rw_scratch(io)
        with P.phase("zpad"):
            z = P.sb([128, 8, 64], F32)
            P.memset("vector", z[:], 0.0, ["z"])
            for k, o in enumerate((0, 64 + T, 4224, 4288 + C)):
                P.dma("sync", fm(hp[:, o:o + 64]), z[:], reads=["z"], writes=[("hpz", k)], sem=f"z{k}")
        def hdst(c0, tw, ic):
            o = 4288 if ic else 64 + c0
            return fm(hp[:, o:o + tw])
        stage_norm(P, io, G, "n_mix0", io["xin"], ALL_TILES,
                   lambda ic: mod_scalars(G, 0, 0, ic)[0], lambda ic: mod_scalars(G, 0, 0, ic)[1], hdst, F32)
        stage_rwkv1(P, io, G, hp, S)
        stage_rwkv2(P, io, G, S, io["xin"], xa)
        outs["y"] = io.output("y", [D, TT])
        with P.phase("copyout"):
            P.dma("sync", outs["y"], xa, writes=["o"], sem="cpa")
    P.close()
    return nc, io.used, list(outs.keys()), P


def fmv(v):
    return np.ascontiguousarray(np.asarray(v, np.float32).reshape(8, 128).T)


def host_consts():
    c = {}
    c["c_ident"] = np.eye(128, dtype=np.float32)
    c["c_ones"] = np.ones((128, 128), np.float32)
    blk = np.zeros((128, 128), np.float32)
    blk[:64, :64] = 1
    blk[64:, 64:] = 1
    c["c_bones"] = blk
    i = np.arange(64)
    us = (i[:, None] < i[None, :]).astype(np.float32)
    ui = (i[:, None] <= i[None, :]).astype(np.float32)
    m = np.zeros((128, 4, 128), np.float32)
    for k, mk in enumerate([us, ui, us.T, ui.T]):
        m[:64, k, :64] = mk
        m[64:, k, 64:] = mk
    c["c_masks"] = m
    Pm = np.zeros((128, 128), np.float32)
    for d in range(128):
        if d % 32 < 16:
            Pm[d, d + 16] = -1.0
        else:
            Pm[d, d - 16] = 1.0
    c["c_perm"] = np.ascontiguousarray(Pm.T)
    sel = np.zeros((128, 2, 128), np.float32)
    sel[64, 0, :] = 1.0
    sel[63, 1, :] = 1.0
    c["c_sel"] = sel
    rm = np.ones((128, 256), np.float32)
    rm[:, ::64] = 0
    c["c_rmask"] = rm
    t = np.arange(T)
    row = (t // 64).astype(np.float32)
    col = (t % 64).astype(np.float32)
    freqs = (np.float32(10000.0) ** (-np.arange(0, 32, 2, dtype=np.float32) / np.float32(32))).astype(np.float32)
    ang = np.zeros((64, T), np.float32)
    for d in range(64):
        pos = row if d < 32 else col
        ang[d] = pos * freqs[d % 16]
    c["cosT"] = np.ascontiguousarray(np.concatenate([np.cos(ang), np.cos(ang)], 0).astype(np.float32))
    c["sinT"] = np.ascontiguousarray(np.concatenate([np.sin(ang), np.sin(ang)], 0).astype(np.float32))
    return c


def host_inputs(inp, b):
    f = lambda k: np.asarray(inp[k], np.float32)
    d = {}
    d["xin"] = np.ascontiguousarray(np.concatenate([f("x")[b].T, f("ctx")[b].T], axis=1))
    d["cvec"] = np.ascontiguousarray(np.stack([fmv(f("c")[b]), fmv(f("c_ctx"))], axis=-1))
    return d


def host_shared(inp):
    f = lambda k: np.asarray(inp[k], np.float32)
    s = dict(host_consts())
    s["w_mod"] = f("w_mod")
    s["b_mod"] = f("b_mod")
    vl = [f("norm_mix")[0], f("norm_mix")[1], f("norm_mlp")[0], f("norm_mlp")[1], f("final_norm")]
    vl += [f("rwkv_mu")[0, j] for j in range(6)]
    vl += [f("rwkv_w0")[0, 0], f("rwkv_w0")[0, 1], f("rwkv_a0")[0, 0], f("rwkv_a0")[0, 1]]
    vl += [f("rwkv_k_k")[0], f("rwkv_k_a")[0], np.zeros(D, np.float32), f("rwkv_r_k")[0].reshape(-1)]
    vl += [np.tile(f("attn_q_norm")[0], 16), np.tile(f("attn_k_norm")[0], 16)]
    assert len(vl) == NV
    s["vecs"] = np.ascontiguousarray(np.stack([fmv(v) for v in vl], axis=1))
    s["mlp_w1"] = f("mlp_w1")
    s["mlp_w2"] = f("mlp_w2")
    for k in ("wr", "wk", "wv", "wo", "w1", "w2", "a1", "a2", "g1", "g2"):
        s["rwkv_" + k] = f("rwkv_" + k)[0]
    lw = f("rwkv_ln_w")[0].reshape(8, 2, 64)
    lb = f("rwkv_ln_b")[0].reshape(8, 2, 64)
    s["lnw_st"] = np.ascontiguousarray(np.repeat(lw.transpose(1, 0, 2), 64, axis=0))
    s["lnb_st"] = np.ascontiguousarray(np.repeat(lb.transpose(1, 0, 2), 64, axis=0))
    wqkv = f("attn_wqkv")[0]
    s["attn_wq"] = np.ascontiguousarray(wqkv[:, :1024])
    wk = wqkv[:, 1024:1280].reshape(D, 4, 64)
    s["attn_wkd"] = np.ascontiguousarray(np.concatenate([wk, wk], axis=2).reshape(D, 512))
    s["attn_wv"] = np.ascontiguousarray(wqkv[:, 1280:1536])
    s["attn_wo"] = f("attn_wo")[0]
    return s


_CACHE = {}


def kernel(**inputs):
    if "prog" not in _CACHE:
        _CACHE["prog"] = build("all")
    nc, used, outnames, _ = _CACHE["prog"]
    shared = host_shared(inputs)
    in_maps = []
    for b in range(NCORES):
        hi = host_inputs(inputs, b)
        hi.update(shared)
        in_maps.append({k: hi[k] for k in used})
    res = run_bass_kernel_spmd(nc, in_maps, core_ids=list(range(NCORES)))
    out = np.stack([np.ascontiguousarray(res.results[b]["y"].T) for b in range(NCORES)], axis=0)
    return out.astype(np.float32)
```

```python
from contextlib import ExitStack, contextmanager
import re as re_mod
import numpy as np
import concourse.bass as bass
import concourse.mybir as mybir
from concourse.bass_utils import run_bass_kernel_spmd

F32 = mybir.dt.float32
BF16 = mybir.dt.bfloat16
AF = mybir.ActivationFunctionType
ALU = mybir.AluOpType
AX = mybir.AxisListType

D = 1024
T = 4096
C = 256
TT = T + C
NCORES = 8
C0 = float(np.exp(-0.5))
NV = 21
ENGS = ("tensor", "vector", "scalar", "gpsimd", "sync")


class Prog:
    def __init__(self, nc):
        self.nc = nc
        self.ges = ExitStack()
        self.sems = {}
        self.cnt = {}
        self.dpool = {False: [], True: []}
        self.seen = {e: {} for e in ENGS}
        self.n = 0
        self.pes = None
        self.total_ops = 0

    def _alloc(self, es, fn, shape, dtype, name):
        self.n += 1
        return es.enter_context(fn(name or f"t{self.n}", list(shape), dtype))

    def gsb(self, shape, dtype, name=None):
        return self._alloc(self.ges, self.nc.sbuf_tensor, shape, dtype, name)

    def sb(self, shape, dtype, name=None):
        return self._alloc(self.pes, self.nc.sbuf_tensor, shape, dtype, name)

    @contextmanager
    def scope(self):
        self.ses = ExitStack()
        yield self
        self.ses.close()
        self.ses = None

    def ssb(self, shape, dtype, name=None):
        return self._alloc(self.ses, self.nc.sbuf_tensor, shape, dtype, name)

    def ps(self, shape, dtype, name=None):
        return self._alloc(self.pes, self.nc.psum_tensor, shape, dtype, name)

    @contextmanager
    def phase(self, name):
        self.ops = []
        self.last_w = {}
        self.readers = {}
        self.last_dma = {}
        self.pes = ExitStack()
        self.pname = name
        yield self
        self._emit()
        self.pes.close()
        self.pes = None

    _PSUM_RE = re_mod.compile(r"^(pp|pa|pb|pq|pf|ps\w*|pX|py|pS|ptr|pw)\d*$")

    ns = None
    ns_set = frozenset()

    def _deps(self, reads, writes):
        if self.ns is not None:
            reads = tuple((r, self.ns) if r in self.ns_set else r for r in reads)
            writes = tuple((w, self.ns) if w in self.ns_set else w for w in writes)
        extra = tuple(r for r in reads if isinstance(r, str) and self._PSUM_RE.match(r) and r not in writes)
        if extra:
            writes = tuple(writes) + extra
        deps = {}
        for r in reads:
            if r in self.last_w:
                deps.setdefault(self.last_w[r], set()).add("RAW")
        for w in writes:
            if w in self.last_w:
                deps.setdefault(self.last_w[w], set()).add("WAW")
            for rd in self.readers.get(w, ()):
                deps.setdefault(rd, set()).add("WAR")
        idx = len(self.ops)
        for r in reads:
            self.readers.setdefault(r, []).append(idx)
        for w in writes:
            self.last_w[w] = idx
            self.readers[w] = []
        return deps

    def op(self, eng, fn, reads=(), writes=()):
        deps = self._deps(tuple(reads), tuple(writes))
        self.ops.append(dict(eng=eng, fn=fn, deps=deps, dma=None))
        return len(self.ops) - 1

    def dma(self, queue, out, in_, reads=(), writes=(), sem=None):
        deps = self._deps(tuple(reads), tuple(writes))
        prev = self.last_dma.get(sem)
        if prev is not None:
            deps.setdefault(prev, set()).add("SER")
        idx = len(self.ops)
        self.last_dma[sem] = idx
        self.ops.append(dict(eng=queue, fn=lambda e: e.dma_start(out=out, in_=in_), deps=deps, dma=sem))
        return idx

    def _emit(self):
        nc = self.nc
        ops = self.ops
        if self.last_dma:
            ops.append(dict(eng="sync", fn=None, deps={i: {"FIN"} for i in self.last_dma.values()}, dma=None))
        self.total_ops += len(ops)

        def needs_wait(x, d, kinds):
            if d["dma"] is not None or x["dma"] is not None:
                return True
            if d["eng"] != x["eng"]:
                return True
            if x["eng"] == "tensor":
                return False
            return bool(kinds & {"RAW", "FIN"})

        signal = [False] * len(ops)
        for x in ops:
            for di, kinds in x["deps"].items():
                d = ops[di]
                if d["dma"] is None and needs_wait(x, d, kinds):
                    signal[di] = True
        dkeys = {}
        nk = {False: 0, True: 0}
        for o in ops:
            if o["dma"] is not None and o["dma"] not in dkeys:
                sw = o["eng"] == "gpsimd"
                dkeys[o["dma"]] = (sw, nk[sw])
                nk[sw] += 1
        for sw in (False, True):
            while len(self.dpool[sw]) < nk[sw]:
                h = self.ges.enter_context(nc.semaphore(f"dq{int(sw)}_{len(self.dpool[sw])}"))
                self.dpool[sw].append([h, 0])
        for e in ENGS:
            if e not in self.sems:
                self.sems[e] = self.ges.enter_context(nc.semaphore(f"e_{e}"))
        token = [None] * len(ops)
        for i, o in enumerate(ops):
            if o["dma"] is not None:
                dk = dkeys[o["dma"]]
                slot = self.dpool[dk[0]][dk[1]]
                slot[1] += 16
                token[i] = (("d", dk), slot[1])
            elif signal[i]:
                self.cnt[o["eng"]] = self.cnt.get(o["eng"], 0) + 1
                token[i] = (("e", o["eng"]), self.cnt[o["eng"]])
        per_eng = {e: [] for e in ENGS}
        for i, o in enumerate(ops):
            per_eng[o["eng"]].append(i)

        def semh(key):
            return self.dpool[key[1][0]][key[1][1]][0] if key[0] == "d" else self.sems[key[1]]

        def run(engname, eng):
            seen = self.seen[engname]
            for i in per_eng[engname]:
                o = ops[i]
                waits = {}
                for di, kinds in o["deps"].items():
                    d = ops[di]
                    if not needs_wait(o, d, kinds):
                        continue
                    key, val = token[di]
                    if waits.get(key, 0) < val:
                        waits[key] = val
                for key, val in waits.items():
                    if seen.get(key, 0) >= val:
                        continue
                    seen[key] = val
                    eng.wait_ge(semh(key), val)
                if o["fn"] is None:
                    continue
                ins = o["fn"](eng)
                if o["dma"] is not None:
                    ins.then_inc(semh(token[i][0]), 16)
                elif signal[i]:
                    ins.then_inc(self.sems[engname], 1)

        with nc.Block() as block:
            @block.sync
            def _(e):
                run("sync", e)

            @block.tensor
            def _(e):
                run("tensor", e)

            @block.vector
            def _(e):
                run("vector", e)

            @block.scalar
            def _(e):
                run("scalar", e)

            @block.gpsimd
            def _(e):
                run("gpsimd", e)

    def close(self):
        self.ges.close()

    def mm(self, out, lhsT, rhs, start, stop, r, w):
        self.op("tensor", lambda e: e.matmul(out, lhsT=lhsT, rhs=rhs, start=start, stop=stop), r, w)

    def tr(self, out, in_, ident, r, w):
        self.op("tensor", lambda e: e.transpose(out, in_, ident), r, w)

    def tt(self, eng, out, in0, in1, op, r, w):
        self.op(eng, lambda e: e.tensor_tensor(out=out, in0=in0, in1=in1, op=op), r, w)

    def ts(self, eng, out, in0, s1, s2, op0, op1, r, w):
        if op1 is None:
            self.op(eng, lambda e: e.tensor_scalar(out=out, in0=in0, scalar1=s1, scalar2=None, op0=op0), r, w)
        else:
            self.op(eng, lambda e: e.tensor_scalar(out=out, in0=in0, scalar1=s1, scalar2=s2, op0=op0, op1=op1), r, w)

    def stt(self, out, in0, scalar, in1, op0, op1, r, w):
        self.op("vector", lambda e: e.scalar_tensor_tensor(out=out, in0=in0, scalar=scalar, in1=in1, op0=op0, op1=op1), r, w)

    def act(self, out, in_, func, r, w, bias=None, scale=None):
        kw = {}
        if bias is not None:
            kw["bias"] = bias
        if scale is not None:
            kw["scale"] = scale
        self.op("scalar", lambda e: e.activation(out=out, in_=in_, func=func, **kw), r, w)

    def cp(self, eng, out, in_, r, w):
        if eng == "scalar":
            self.op(eng, lambda e: e.activation(out=out, in_=in_, func=AF.Copy), r, w)
        else:
            self.op(eng, lambda e: e.tensor_copy(out=out, in_=in_), r, w)

    def memset(self, eng, ap, val, w):
        self.op(eng, lambda e: e.memset(ap, val), (), w)


def fm(ap2d):
    return ap2d.rearrange("(c p) n -> p c n", p=128)


LAT_TILES = [(i * 512, 512, False) for i in range(8)]
ALL_TILES = LAT_TILES + [(T, 256, True)]


def stage_init(P, io, G):
    nc = P.nc
    G["identf"] = P.gsb([128, 128], F32, "identf")
    G["identb"] = P.gsb([128, 128], BF16, "identb")
    G["onesb"] = P.gsb([128, 128], BF16, "onesb")
    G["bones"] = P.gsb([128, 128], BF16, "bones")
    G["masks"] = P.gsb([128, 4, 128], BF16, "masks")
    G["perm"] = P.gsb([128, 128], BF16, "perm")
    G["rmask"] = P.gsb([128, 256], F32, "rmask")
    G["vec"] = P.gsb([128, NV, 8], F32, "vec")
    G["modv"] = P.gsb([128, 2, 6, 8, 2], F32, "modv")
    G["gg"] = P.gsb([128, 2, 2, 8, 2], F32, "gg")
    with P.phase("init"):
        P.dma("sync", G["identf"][:], io["c_ident"], writes=["identf"], sem="identf")
        P.dma("sync", G["rmask"][:], io["c_rmask"], writes=["rmask"], sem="rmask")
        P.dma("sync", G["vec"][:], io["vecs"], writes=["vec"], sem="vec")
        P.dma("gpsimd", G["identb"][:], io["c_ident"], writes=["identb"], sem="identb")
        P.dma("gpsimd", G["onesb"][:], io["c_ones"], writes=["onesb"], sem="onesb")
        P.dma("gpsimd", G["bones"][:], io["c_bones"], writes=["bones"], sem="bones")
        P.dma("gpsimd", G["masks"][:], io["c_masks"], writes=["masks"], sem="masks")
        P.dma("gpsimd", G["perm"][:], io["c_perm"], writes=["perm"], sem="perm")
        vec = G["vec"]
        P.ts("vector", vec[:, 17, :], vec[:, 16, :], -1.0, 1.0, ALU.mult, ALU.add, ["vec"], ["vec"])
        sv = P.sb([128, 8, 2], F32)
        svs = P.sb([128, 8, 2], F32)
        P.dma("sync", sv[:], io["cvec"], writes=["sv"], sem="sv")
        P.act(svs[:], sv[:], AF.Silu, ["sv"], ["svs"])
        brow = P.sb([2, 2 * 6144], F32)
        row = P.sb([2, 2 * 6144], F32)
        P.dma("sync", brow[:], io["b_mod"].rearrange("l n -> (l n)").partition_broadcast(2), writes=["brow"], sem="brow")
        wt = [P.sb([128, 8, 512], F32) for _ in range(2)]
        psr = [P.ps([128, 512], F32) for _ in range(2)]
        pst = P.ps([128, 512], F32)
        k = 0
        for l in range(2):
            for nb in range(12):
                b = k % 2
                k += 1
                P.dma("sync", wt[b][:], fm(io["w_mod"][l, :, nb * 512:(nb + 1) * 512]), writes=[f"wt{b}"], sem=f"wt{b}")
                for c in range(8):
                    P.mm(psr[b][0:2, :], svs[:, c, :], wt[b][:, c, :], c == 0, c == 7, ["svs", f"wt{b}"], [f"psr{b}"])
                o = l * 6144 + nb * 512
                P.tt("vector", row[:, o:o + 512], psr[b][0:2, :], brow[:, o:o + 512], ALU.add, [f"psr{b}", "brow"], ["row"])
        for l in range(2):
            for blk in range(48):
                o = l * 6144 + blk * 128
                P.tr(pst[:, l * 96 + blk * 2:l * 96 + blk * 2 + 2], row[0:2, o:o + 128], G["identf"][0:2, 0:2], ["row", "identf"], ["pst"])
        P.cp("vector", G["modv"][:].rearrange("p l m c j -> p (l m c j)"), pst[:, 0:192], ["pst"], ["modv"])
        modv, gg = G["modv"], G["gg"]
        for l in range(2):
            for kind in range(2):
                sc = modv[:, l, 1 + 3 * kind, :, :]
                nv = vec[:, (0 if kind == 0 else 2) + l, :].unsqueeze(2).broadcast_to([128, 8, 2])
                P.ts("vector", gg[:, l, kind, :, :], sc, 1.0, None, ALU.add, None, ["modv"], ["gg"])
                P.tt("vector", gg[:, l, kind, :, :], gg[:, l, kind, :, :], nv, ALU.mult, ["gg", "vec"], ["gg"])


def mod_scalars(G, l, kind, isctx):
    j = 1 if isctx else 0
    gains = [G["gg"][:, l, kind, c, j:j + 1] for c in range(8)]
    shifts = [G["modv"][:, l, 3 * kind, c, j:j + 1] for c in range(8)]
    gates = [G["modv"][:, l, 3 * kind + 2, c, j:j + 1] for c in range(8)]
    return gains, shifts, gates


def stage_norm(P, io, G, name, src, tiles, gains_fn, shifts_fn, dst_fn, out_dtype):
    with P.phase(name):
        xt = [P.sb([128, 8, 512], F32) for _ in range(2)]
        sq = P.sb([128, 8, 512], BF16)
        lnv = P.sb([128, 512], F32)
        rstd = P.sb([128, 512], F32)
        tmp = [P.sb([128, 512], F32) for _ in range(2)]
        ho = [P.sb([128, 8, 512], out_dtype) for _ in range(2)]
        ps = [P.ps([128, 512], F32) for _ in range(2)]

        def load(i):
            c0, tw, _ = tiles[i]
            b = i % 2
            P.dma("sync", xt[b][:, :, :tw], fm(src[:, c0:c0 + tw]), writes=[f"xt{b}"], sem=f"xt{b}")

        load(0)
        for i, (c0, tw, isctx) in enumerate(tiles):
            b = i % 2
            if i + 1 < len(tiles):
                load(i + 1)
            gains = gains_fn(isctx)
            shifts = shifts_fn(isctx)
            P.act(sq[:, :, :tw], xt[b][:, :, :tw], AF.Square, [f"xt{b}"], ["sq"])
            for c in range(8):
                P.mm(ps[b][:, :tw], G["onesb"][:], sq[:, c, :tw], c == 0, c == 7, ["sq", "onesb"], [f"ps{b}"])
            P.act(lnv[:, :tw], ps[b][:, :tw], AF.Ln, [f"ps{b}"], ["lnv"], bias=1e-6, scale=1.0 / D)
            P.act(rstd[:, :tw], lnv[:, :tw], AF.Exp, ["lnv"], ["rstd"], scale=-0.5)
            for c in range(8):
                if shifts is None:
                    P.stt(ho[b][:, c, :tw], xt[b][:, c, :tw], gains[c], rstd[:, :tw], ALU.mult, ALU.mult,
                          [f"xt{b}", "rstd", "vec", "gg"], [f"ho{b}"])
                else:
                    t = tmp[c % 2]
                    P.stt(t[:, :tw], xt[b][:, c, :tw], gains[c], rstd[:, :tw], ALU.mult, ALU.mult,
                          [f"xt{b}", "rstd", "vec", "gg"], [f"tmp{c % 2}"])
                    P.act(ho[b][:, c, :tw], t[:, :tw], AF.Identity, [f"tmp{c % 2}", "modv"], [f"ho{b}"], bias=shifts[c])
            P.dma("sync", dst_fn(c0, tw, isctx), ho[b][:, :, :tw], reads=[f"ho{b}"], writes=[("dst", i)], sem=f"ho{b}")


def stage_mlp(P, io, G, l, tiles, xa, hb):
    for half in range(2):
        with P.phase(f"mlp{l}{half}"):
            w1 = P.sb([128, 8, 2048], BF16)
            w2 = P.sb([128, 16, 1024], BF16)
            for q in range(2):
                P.dma("gpsimd", w1[:, :, q * 1024:(q + 1) * 1024],
                      fm(io["mlp_w1"][l, :, half * 2048 + q * 1024: half * 2048 + (q + 1) * 1024]), writes=["w1"], sem=f"w1{q}")
                P.dma("gpsimd", w2[:, q * 8:(q + 1) * 8, :],
                      io["mlp_w2"][l, half * 2048 + q * 1024: half * 2048 + (q + 1) * 1024, :].rearrange("(f p) n -> p f n", p=128),
                      writes=["w2"], sem=f"w2{q}")
            xt = [P.sb([128, 8, 512], F32) for _ in range(2)]
            ht = [P.sb([128, 8, 512], BF16) for _ in range(2)]
            h1 = P.sb([128, 16, 512], BF16)
            r1 = [P.sb([128, 512], F32) for _ in range(2)]
            ps = [P.ps([128, 512], F32) for _ in range(4)]

            def load(i):
                c0, tw, _ = tiles[i]
                b = i % 2
                P.dma("sync", ht[b][:, :, :tw], fm(hb[:, c0:c0 + tw]), writes=[f"ht{b}"], sem=f"ht{b}")
                P.dma("sync", xt[b][:, :, :tw], fm(xa[:, c0:c0 + tw]), reads=[("xa", i)], writes=[f"xt{b}"], sem=f"xt{b}")

            load(0)
            for i, (c0, tw, isctx) in enumerate(tiles):
                b = i % 2
                if i + 1 < len(tiles):
                    load(i + 1)
                _, _, gates = mod_scalars(G, l, 1, isctx)
                for fc in range(16):
                    pb = fc % 2
                    for c in range(8):
                        P.mm(ps[pb][:, :tw], w1[:, c, fc * 128:(fc + 1) * 128], ht[b][:, c, :tw], c == 0, c == 7,
                             ["w1", f"ht{b}"], [f"ps{pb}"])
                    P.act(r1[pb][:, :tw], ps[pb][:, :tw], AF.Relu, [f"ps{pb}"], [f"r1{pb}"])
                    P.tt("gpsimd", h1[:, fc, :tw], r1[pb][:, :tw], r1[pb][:, :tw], ALU.mult, [f"r1{pb}"], [("h1", fc)])
                for oc in range(8):
                    pb = 2 + oc % 2
                    for fc in range(16):
                        P.mm(ps[pb][:, :tw], w2[:, fc, oc * 128:(oc + 1) * 128], h1[:, fc, :tw], fc == 0, fc == 15,
                             ["w2", ("h1", fc)], [f"ps{pb}"])
                    P.stt(xt[b][:, oc, :tw], ps[pb][:, :tw], gates[oc], xt[b][:, oc, :tw], ALU.mult, ALU.add,
                          [f"ps{pb}", f"xt{b}", "modv"], [f"xt{b}"])
                P.dma("sync", fm(xa[:, c0:c0 + tw]), xt[b][:, :, :tw], reads=[f"xt{b}"], writes=[("xa", i)], sem=f"xt{b}")


RW_ORDER1 = [(True, 0)] + [(False, i) for i in range(16)]
RW_ORDER2 = [(True, 0)] + [(False, i) for i in range(15, -1, -1)]


def rw_scratch(io):
    S = {}
    S["yp"] = io.scratch("rw_yp", [17, 8, 128, 256], F32)
    S["sadd"] = io.scratch("rw_sadd", [17, 8, 128, 256], F32)
    S["vst"] = io.scratch("rw_vst", [17, 8, 128, 256], F32)
    S["gst"] = io.scratch("rw_gst", [17, 8, 128, 256], F32)
    S["gyb"] = io.scratch("rw_gyb", [17, 8, 128, 512], BF16)
    S["gsb"] = io.scratch("rw_gsb", [17, 8, 128, 512], BF16)
    S["gamb"] = io.scratch("rw_gamb", [17, 128, 32], F32)
    S["bon"] = io.scratch("rw_bon", [17, 128, 32], F32)
    S["ops"] = io.scratch("rw_ops", [17, 8, 128, 2048], BF16)
    S["vb"] = io.scratch("rw_vb", [17, 8, 128, 256], BF16)
    S["gam"] = io.scratch("rw_gam", [17, 8, 128, 8], F32)
    return S


def stage_rwkv1(P, io, G, hp, S, dbg=None):
    vec, masks, identb, identf, bones, onesb, rmask = (G[k] for k in ("vec", "masks", "identb", "identf", "bones", "onesb", "rmask"))
    with P.phase("rwkv1"):
        wr = P.sb([128, 8, 1024], BF16)
        wk = P.sb([128, 8, 1024], BF16)
        wv = P.sb([128, 8, 1024], BF16)
        for w, nm in ((wr, "rwkv_wr"), (wk, "rwkv_wk"), (wv, "rwkv_wv")):
            P.dma("gpsimd", w[:], fm(io[nm]), writes=[nm], sem=nm)
        lw1 = P.sb([128, 8, 128], BF16)
        la1 = P.sb([128, 8, 128], BF16)
        g1 = P.sb([128, 8, 128], BF16)
        for d in range(2):
            P.dma("gpsimd", lw1[:, :, d * 64:(d + 1) * 64], io["rwkv_w1"][d].rearrange("(c p) j -> p c j", p=128), writes=["lw1"], sem=f"lw1{d}")
            P.dma("gpsimd", la1[:, :, d * 64:(d + 1) * 64], io["rwkv_a1"][d].rearrange("(c p) j -> p c j", p=128), writes=["la1"], sem=f"la1{d}")
        P.dma("gpsimd", g1[:], io["rwkv_g1"].rearrange("(c p) j -> p c j", p=128), writes=["g1"], sem="g1")
        w2s = P.sb([128, 1024], BF16)
        a2s = P.sb([128, 1024], BF16)
        g2 = P.sb([128, 1024], BF16)
        P.dma("gpsimd", w2s[:], io["rwkv_w2"].rearrange("d j f -> (d j) f"), writes=["w2s"], sem="w2s")
        P.dma("gpsimd", a2s[:], io["rwkv_a2"].rearrange("d j f -> (d j) f"), writes=["a2s"], sem="a2s")
        P.dma("gpsimd", g2[:], io["rwkv_g2"], writes=["g2"], sem="g2")

        hh = P.sb([128, 8, 384], F32)
        xx = P.sb([128, 8, 256], F32)
        xr = P.sb([128, 8, 256], BF16)
        xk = P.sb([128, 8, 256], BF16)
        xv = P.sb([128, 8, 256], BF16)
        xrot = P.sb([128, 8, 256], BF16)
        lwt = P.sb([128, 256], BF16)
        lat = P.sb([128, 256], BF16)
        sg = P.sb([128, 256], BF16)
        f32t = {}
        for nm in ("r", "k", "sw0", "sw1", "ag0", "ag1", "kq", "lnv", "rs", "kkn", "fac", "kd0", "kd1", "b0", "b1",
                   "L", "Lx", "Lb", "E1", "E2", "E3", "ks"):
            f32t[nm] = P.sb([128, 256], F32, "t_" + nm)
        sqb = P.sb([128, 256], BF16)
        RK = P.sb([128, 4, 2, 64], BF16)
        VTbd = P.sb([128, 4, 128], F32)
        GTbd = P.sb([128, 4, 128], F32)
        Vf = P.sb([128, 4, 64], F32)
        Gf = P.sb([128, 4, 64], F32)
        YPs = P.sb([128, 4, 64], F32)
        SAs = P.sb([128, 4, 64], F32)
        gamb_t = P.sb([128, 8, 4], F32)
        bon_t = P.sb([128, 8, 4], F32)
        Sf = P.sb([128, 8, 64], BF16)
        ARq = [[P.sb([128, 4, 2, 128], BF16, f"AR{q}{d}") for d in range(2)] for q in range(2)]
        KTq = [[P.sb([128, 4, 128], BF16, f"KT{q}{d}") for d in range(2)] for q in range(2)]
        BTq = [[P.sb([128, 4, 128], BF16, f"BT{q}{d}") for d in range(2)] for q in range(2)]
        Vbq = [P.sb([128, 4, 64], BF16, f"Vb{q}") for q in range(3)]
        gamq = [[P.sb([128, 4], F32, f"gam{q}{d}") for d in range(2)] for q in range(3)]
        inv = []
        for d in range(2):
            st = {}
            for nm, shp in (("Atok", [128, 4, 128]), ("Btok", [128, 4, 128]), ("MQ", [128, 4, 256]), ("MWa", [128, 4, 2, 128]),
                            ("MWb", [128, 4, 2, 128]), ("MTa", [128, 4, 128]), ("MTb", [128, 4, 128])):
                st[nm] = P.sb(shp, BF16, f"i{d}_{nm}")
            inv.append(st)
        fin = []
        for q in range(2):
            row = []
            for d in range(2):
                st = {}
                for nm, shp in (("Ktok", [128, 4, 128]), ("NP", [128, 4, 256]), ("XW", [128, 4, 256]), ("NVb", [128, 4, 64]),
                                ("GY", [128, 4, 128]), ("GS", [128, 4, 128])):
                    st[nm] = P.sb(shp, BF16, f"f{q}{d}_{nm}")
                row.append(st)
            fin.append(row)
        ppt = [P.ps([128, 512], F32) for _ in range(2)]
        pp = [t_[:, 0:256] for t_ in ppt]
        pf = P.ps([128, 512], F32)
        pb = [P.ps([128, 512], F32) for _ in range(5)]
        cnt = {"pp": 0, "pb": 0}
        nmod = {"pp": 2, "pb": 5}

        def nxt(kind):
            i = cnt[kind] % nmod[kind]
            cnt[kind] += 1
            return i

        for q in range(2):
            for d in range(2):
                P.memset("gpsimd", ARq[q][d][:], 0.0, [f"AR{q}{d}"])
                P.memset("gpsimd", KTq[q][d][:], 0.0, [f"KT{q}{d}"])
                P.memset("gpsimd", BTq[q][d][:], 0.0, [f"BT{q}{d}"])
        P.memset("gpsimd", RK[:], 0.0, ["RK"])
        P.memset("gpsimd", VTbd[:], 0.0, ["VTbd"])
        P.memset("gpsimd", GTbd[:], 0.0, ["GTbd"])
        P.memset("gpsimd", Sf[:], 0.0, [("Sf", p) for p in range(8)])

        def v3(ap):
            return ap.rearrange("p (u s) -> p u s", s=64)

        def u128(ap):
            return ap.rearrange("p (u x) -> p u x", x=128)

        def load_hh(ti):
            isctx, idx = RW_ORDER1[ti]
            off = 4288 if isctx else 64 + 256 * idx
            P.dma("sync", hh[:], fm(hp[:, off - 64: off + 320]), writes=["hh"], sem="hh")

        def proj8(w_cols_fn, xb, bn, extra_r):
            i = nxt("pp")
            for c in range(8):
                P.mm(pp[i], w_cols_fn(c), xb[:, c, :], c == 0, c == 7, [(bn, c)] + extra_r, [f"pp{i}"])
            return i

        def tprep(ti):
            isctx, idx = RW_ORDER1[ti]
            hc = hh[:, :, 64:320]
            XXW = [("xx", c) for c in range(8)]
            if not isctx:
                h4 = hh[:, :, 64:320].rearrange("p c (r w) -> p c r w", w=64)
                x4 = xx[:].rearrange("p c (r w) -> p c r w", w=64)
                P.tt("vector", x4[:, 0:2, :, 1:64], h4[:, 0:2, :, 0:63], h4[:, 0:2, :, 1:64], ALU.subtract, ["hh"], XXW[0:2])
                P.ts("gpsimd", x4[:, 0:2, :, 0:1], h4[:, 0:2, :, 0:1], -1.0, 0.0, ALU.mult, ALU.add, ["hh"], [("xxe", 0)])
                P.tt("vector", x4[:, 2:4, :, 0:63], h4[:, 2:4, :, 1:64], h4[:, 2:4, :, 0:63], ALU.subtract, ["hh"], XXW[2:4])
                P.ts("gpsimd", x4[:, 2:4, :, 63:64], h4[:, 2:4, :, 63:64], -1.0, 0.0, ALU.mult, ALU.add, ["hh"], [("xxe", 1)])
                P.tt("gpsimd", xx[:, 4:6, :], hh[:, 4:6, 0:256], hh[:, 4:6, 64:320], ALU.subtract, ["hh"], XXW[4:6])
                P.tt("gpsimd", xx[:, 6:8, :], hh[:, 6:8, 128:384], hh[:, 6:8, 64:320], ALU.subtract, ["hh"], XXW[6:8])
            else:
                P.tt("vector", xx[:, 0:4, :], hh[:, 0:4, 63:319], hh[:, 0:4, 64:320], ALU.subtract, ["hh"], XXW[0:4] + [("xxe", 0)])
                P.tt("gpsimd", xx[:, 4:8, :], hh[:, 4:8, 65:321], hh[:, 4:8, 64:320], ALU.subtract, ["hh"], XXW[4:8] + [("xxe", 1)])
            yield

            def mk_xj(j, buf, bn):
                for c in range(8):
                    P.stt(buf[:, c, :], xx[:, c, :], vec[:, 5 + j, c:c + 1], hc[:, c, :], ALU.mult, ALU.add,
                          [("xx", c), ("xxe", 0), ("xxe", 1), "hh", "vec"], [(bn, c)])

            mk_xj(1, xrot, "xrot")
            yield
            i = proj8(lambda c: lw1[:, c, :], xrot, "xrot", ["lw1"])
            P.act(lwt[:], pp[i], AF.Tanh, [f"pp{i}"], ["lwt"])
            yield
            mk_xj(4, xrot, "xrot")
            yield
            i = proj8(lambda c: la1[:, c, :], xrot, "xrot", ["la1"])
            P.cp("scalar", lat[:], pp[i], [f"pp{i}"], ["lat"])
            yield
            mk_xj(5, xrot, "xrot")
            yield
            i = proj8(lambda c: g1[:, c, :], xrot, "xrot", ["g1"])
            P.act(sg[:], pp[i], AF.Sigmoid, [f"pp{i}"], ["sg"])
            yield
            mk_xj(0, xr, "xr")
            yield
            mk_xj(2, xk, "xk")
            yield
            mk_xj(3, xv, "xv")
            if ti + 1 < len(RW_ORDER1):
                load_hh(ti + 1)
            yield

        def prep(ti, oc, q, z):
            isctx, idx = RW_ORDER1[ti]
            tg = 16 if isctx else idx
            cs = slice(oc * 128, (oc + 1) * 128)
            t = f32t
            AR, KT, BT, Vb, gam = ARq[q], KTq[q], BTq[q], Vbq[z], gamq[z]
            i = proj8(lambda c: wr[:, c, cs], xr, "xr", ["rwkv_wr"])
            P.cp("scalar", t["r"][:], pp[i], [f"pp{i}"], ["r"])
            i = proj8(lambda c: wk[:, c, cs], xk, "xk", ["rwkv_wk"])
            P.cp("scalar", t["k"][:], pp[i], [f"pp{i}"], ["k"])
            i = proj8(lambda c: wv[:, c, cs], xv, "xv", ["rwkv_wv"])
            vt4 = VTbd[:].rearrange("p u (h s) -> p u h s", h=2)
            for h2 in range(2):
                sl = slice(h2 * 64, (h2 + 1) * 64)
                P.cp("scalar", vt4[sl, :, h2, :], v3(pp[i][sl, :]), [f"pp{i}"], ["VTbd"])
            i = nxt("pp")
            P.mm(pp[i], g2[:, cs], sg[:], True, True, ["g2", "sg"], [f"pp{i}"])
            gt4 = GTbd[:].rearrange("p u (h s) -> p u h s", h=2)
            for h2 in range(2):
                sl = slice(h2 * 64, (h2 + 1) * 64)
                P.cp("scalar", gt4[sl, :, h2, :], v3(pp[i][sl, :]), [f"pp{i}"], ["GTbd"])
            yield
            j = nxt("pb")
            for u in range(4):
                P.tr(pb[j][:, u * 128:(u + 1) * 128], VTbd[:, u, :], identf[:], ["VTbd", "identf"], [f"pb{j}"])
            pv = u128(pb[j][:])
            for h2 in range(2):
                sl = slice(h2 * 64, (h2 + 1) * 64)
                P.cp("scalar", Vf[sl, :, :], pv[sl, :, h2 * 64:(h2 + 1) * 64], [f"pb{j}"], ["Vf"])
            P.cp("gpsimd", Vb[:], Vf[:], ["Vf"], [f"Vb{z}"])
            P.dma("sync", S["vst"][tg, oc].rearrange("p (u s) -> p u s", s=64), Vf[:], reads=["Vf"], writes=[("vst", tg, oc)], sem="Vf")
            j = nxt("pb")
            for u in range(4):
                P.tr(pb[j][:, u * 128:(u + 1) * 128], GTbd[:, u, :], identf[:], ["GTbd", "identf"], [f"pb{j}"])
            pv = u128(pb[j][:])
            for h2 in range(2):
                sl = slice(h2 * 64, (h2 + 1) * 64)
                P.cp("scalar", Gf[sl, :, :], pv[sl, :, h2 * 64:(h2 + 1) * 64], [f"pb{j}"], ["Gf"])
            P.dma("sync", S["gst"][tg, oc].rearrange("p (u s) -> p u s", s=64), Gf[:], reads=["Gf"], writes=[("gst", tg, oc)], sem="Gf")
            yield
            for d in range(2):
                dl = slice(d * 64, (d + 1) * 64)
                i = nxt("pp")
                P.mm(pp[i], w2s[dl, cs], lwt[dl, :], True, True, ["w2s", "lwt"], [f"pp{i}"])
                P.act(t[f"sw{d}"][:], pp[i], AF.Sigmoid, [f"pp{i}", "vec"], [f"sw{d}"], bias=vec[:, 11 + d, oc:oc + 1])
                i = nxt("pp")
                P.mm(pp[i], a2s[dl, cs], lat[dl, :], True, True, ["a2s", "lat"], [f"pp{i}"])
                P.act(t[f"ag{d}"][:], pp[i], AF.Sigmoid, [f"pp{i}", "vec"], [f"ag{d}"], bias=vec[:, 13 + d, oc:oc + 1])
            yield
            P.ts("vector", t["kq"][:], t["k"][:], vec[:, 15, oc:oc + 1], None, ALU.mult, None, ["k", "vec"], ["kq"])
            P.act(sqb[:], t["kq"][:], AF.Square, ["kq"], ["sqb"])
            i = nxt("pp")
            P.mm(pp[i], bones[:], sqb[:], True, True, ["bones", "sqb"], [f"pp{i}"])
            P.act(t["lnv"][:], pp[i], AF.Ln, [f"pp{i}"], ["lnv"], bias=1e-12)
            P.act(t["rs"][:], t["lnv"][:], AF.Exp, ["lnv"], ["rs"], scale=-0.5)
            P.tt("gpsimd", t["kkn"][:], t["kq"][:], t["rs"][:], ALU.mult, ["kq", "rs"], ["kkn"])
            for d in range(2):
                sw, ag, kd, bb = t[f"sw{d}"], t[f"ag{d}"], t[f"kd{d}"], t[f"b{d}"]
                EE = "gpsimd" if d == 0 else "vector"
                P.ts(EE, t["fac"][:], ag[:], vec[:, 16, oc:oc + 1], vec[:, 17, oc:oc + 1], ALU.mult, ALU.add, [f"ag{d}", "vec"], ["fac"])
                P.tt(EE, kd[:], t["k"][:], t["fac"][:], ALU.mult, ["k", "fac"], [f"kd{d}"])
                P.tt(EE, bb[:], t["kkn"][:], ag[:], ALU.mult, ["kkn", f"ag{d}"], [f"b{d}"])
                P.op("vector", lambda e, sw=sw: e.tensor_tensor_scan(out=t["L"][:], data0=rmask[:], data1=sw[:], initial=0.0,
                                                                      op0=ALU.mult, op1=ALU.add), [f"sw{d}", "rmask"], ["L"])
                L3 = v3(t["L"][:])
                if d == 0:
                    P.tt(EE, t["Lx"][:], t["L"][:], sw[:], ALU.subtract, ["L", f"sw{d}"], ["Lx"])
                    Li, Lin = t["L"], "L"
                else:
                    P.tt(EE, v3(t["Lx"][:]), L3[:, :, 63:64].broadcast_to([128, 4, 64]), L3, ALU.subtract, ["L"], ["Lx"])
                    P.tt(EE, t["Lb"][:], t["Lx"][:], sw[:], ALU.add, ["Lx", f"sw{d}"], ["Lb"])
                    Li, Lin = t["Lb"], "Lb"
                P.act(t["E1"][:], Li[:], AF.Exp, [Lin], ["E1"], scale=-C0)
                P.act(t["E3"][:], Li[:], AF.Exp, [Lin], ["E3"], scale=C0)
                P.act(t["E2"][:], t["Lx"][:], AF.Exp, ["Lx"], ["E2"], scale=-C0)
                ar5 = AR[d][:].rearrange("p u a (h s) -> p u a h s", h=2)
                kt4 = KT[d][:].rearrange("p u (h s) -> p u h s", h=2)
                bt4 = BT[d][:].rearrange("p u (h s) -> p u h s", h=2)
                for h2 in range(2):
                    sl = slice(h2 * 64, (h2 + 1) * 64)
                    P.stt(ar5[sl, :, 0, h2, :], v3(t["kkn"][sl, :]), -1.0, v3(t["E2"][sl, :]), ALU.mult, ALU.mult, ["kkn", "E2"], [f"AR{q}{d}"])
                    P.tt(EE, ar5[sl, :, 1, h2, :], v3(t["r"][sl, :]), v3(t["E1"][sl, :]), ALU.mult, ["r", "E1"], [f"AR{q}{d}"])
                    P.tt(EE, kt4[sl, :, h2, :], v3(kd[sl, :]), v3(t["E3"][sl, :]), ALU.mult, [f"kd{d}", "E3"], [f"KT{q}{d}"])
                    P.tt(EE, bt4[sl, :, h2, :], v3(bb[sl, :]), v3(t["E3"][sl, :]), ALU.mult, [f"b{d}", "E3"], [f"BT{q}{d}"])
                E13 = v3(t["E1"][:])
                gsrc = E13[:, :, 63] if d == 0 else E13[:, :, 0]
                P.cp("vector", gam[d][:], gsrc, ["E1"], [f"gam{z}{d}"])
                if d == 1:
                    P.cp("gpsimd", gamb_t[:, oc, :], gam[1][:], [f"gam{z}1"], ["gamb_t"])
                yield
            P.tt("gpsimd", t["ks"][:], t["kd0"][:], t["kd1"][:], ALU.add, ["kd0", "kd1"], ["ks"])
            for h2 in range(2):
                sl = slice(h2 * 64, (h2 + 1) * 64)
                P.stt(RK[sl, :, h2, :], v3(t["r"][sl, :]), vec[sl, 18, oc:oc + 1], v3(t["ks"][sl, :]), ALU.mult, ALU.mult, ["r", "ks", "vec"], ["RK"])
            i = nxt("pp")
            for u in range(4):
                P.mm(pp[i][:, u:u + 1], RK[:, u, :, :].rearrange("p h s -> p (h s)"), onesb[:, 0:1], True, True, ["RK", "onesb"], [f"pp{i}"])
            P.cp("scalar", bon_t[:, oc, :], pp[i][:, 0:4], [f"pp{i}"], ["bon_t"])
            if oc == 7:
                P.dma("sync", S["gamb"][tg], gamb_t[:].rearrange("p a b -> p (a b)"), reads=["gamb_t"], writes=[("gamb", tg)], sem="gamb_t")
                P.dma("sync", S["bon"][tg], bon_t[:].rearrange("p a b -> p (a b)"), reads=["bon_t"], writes=[("bon", tg)], sem="bon_t")
            yield

        def chain(ti, oc, q, d, z):
            AR, KT, BT, Vb = ARq[q][d], KTq[q][d], BTq[q][d], Vbq[z]
            ARn, KTn, BTn, Vbn = f"AR{q}{d}", f"KT{q}{d}", f"BT{q}{d}", f"Vb{z}"
            iv, fn = inv[d], fin[q][d]
            IR = lambda nm: f"i{d}_{nm}"
            FR = lambda nm: f"f{q}{d}_{nm}"
            mS, mC = (0, 2) if d == 0 else (2, 0)
            mSI = masks[:, mS:mS + 2, :].rearrange("p a b -> p (a b)").unsqueeze(1).broadcast_to([128, 4, 256])
            mCb = masks[:, mC, :].unsqueeze(1).broadcast_to([128, 4, 128])
            idb = identb[:].unsqueeze(1).broadcast_to([128, 4, 128])
            for src, srcn, dst, dstn in ((AR[:, :, 0, :], ARn, iv["Atok"], IR("Atok")), (BT[:], BTn, iv["Btok"], IR("Btok")),
                                         (KT[:], KTn, fn["Ktok"], FR("Ktok"))):
                j = nxt("pb")
                pbt = pb[j][:].bitcast(BF16)
                for u in range(4):
                    P.tr(pbt[:, u * 128:(u + 1) * 128], src[:, u, :], identb[:], [srcn, "identb"], [f"pb{j}"])
                P.cp("scalar", dst[:].rearrange("p u x -> p (u x)"), pbt[:, 0:512], [f"pb{j}"], [dstn])
            mSb = masks[:, mS, :].unsqueeze(1).broadcast_to([128, 4, 128])
            mIb = masks[:, mS + 1, :].unsqueeze(1).broadcast_to([128, 4, 128])

            def two_bank(mm_fn):
                j0, j1 = nxt("pb"), nxt("pb")
                for u in range(4):
                    mm_fn(u, pb[j0][:, u * 128:(u + 1) * 128], f"pb{j0}", pb[j1][:, u * 128:(u + 1) * 128], f"pb{j1}")
                return j0, j1

            for lhs, lhsn, dst, dstn in ((BT, BTn, iv["MQ"], IR("MQ")), (KT, KTn, fn["NP"], FR("NP"))):
                def mm_ab(u, o0, n0, o1, n1, lhs=lhs, lhsn=lhsn):
                    P.mm(o0, lhs[:, u, :], AR[:, u, 0, :], True, True, [lhsn, ARn], [n0])
                    P.mm(o1, lhs[:, u, :], AR[:, u, 1, :], True, True, [lhsn, ARn], [n1])
                j0, j1 = two_bank(mm_ab)
                P.tt("vector", dst[:, :, 0:128], u128(pb[j0][:]), mSb, ALU.mult, [f"pb{j0}", "masks"], [dstn])
                P.tt("vector", dst[:, :, 128:256], u128(pb[j1][:]), mIb, ALU.mult, [f"pb{j1}", "masks"], [dstn])
            j = nxt("pb")
            for u in range(4):
                P.mm(pb[j][:, u * 128:(u + 1) * 128], AR[:, u, 0, :], BT[:, u, :], True, True, [ARn, BTn], [f"pb{j}"])
            cur, curn, nx, nxn = iv["MWa"], IR("MWa"), iv["MWb"], IR("MWb")
            P.tt("vector", cur[:, :, 0, :], u128(pb[j][:]), mCb, ALU.mult, [f"pb{j}", "masks"], [curn])
            yield
            j = nxt("pb")
            for u in range(4):
                P.mm(pb[j][:, u * 128:(u + 1) * 128], iv["MQ"][:, u, 0:128], cur[:, u, 0, :], True, True, [IR("MQ"), curn], [f"pb{j}"])
            P.cp("scalar", nx[:, :, 0, :], u128(pb[j][:]), [f"pb{j}"], [nxn])
            P.tt("gpsimd", nx[:, :, 1, :], cur[:, :, 0, :], idb, ALU.add, [curn, "identb"], [nxn])
            j = nxt("pb")
            for u in range(4):
                P.mm(pb[j][:, u * 128:(u + 1) * 128], cur[:, u, 0, :], iv["MQ"][:, u, 0:128], True, True, [IR("MQ"), curn], [f"pb{j}"])
            curT, curTn, nxT, nxTn = iv["MTa"], IR("MTa"), iv["MTb"], IR("MTb")
            P.cp("scalar", curT[:], u128(pb[j][:]), [f"pb{j}"], [curTn])
            cur, curn, nx, nxn = nx, nxn, cur, curn
            yield
            for lev in range(1, 5):
                def mm_lev(u, o0, n0, o1, n1, cur=cur, curn=curn, curT=curT, curTn=curTn):
                    P.mm(o0, curT[:, u, :], cur[:, u, 0, :], True, True, [curTn, curn], [n0])
                    P.mm(o1, curT[:, u, :], cur[:, u, 1, :], True, True, [curTn, curn], [n1])
                j0, j1 = two_bank(mm_lev)
                P.cp("scalar", nx[:, :, 0, :], u128(pb[j0][:]), [f"pb{j0}"], [nxn])
                P.tt("vector", nx[:, :, 1, :], u128(pb[j1][:]), cur[:, :, 1, :], ALU.add, [f"pb{j1}", curn], [nxn])
                j = nxt("pb")
                for u in range(4):
                    P.mm(pb[j][:, u * 128:(u + 1) * 128], cur[:, u, 0, :], curT[:, u, :], True, True, [curn, curTn], [f"pb{j}"])
                P.cp("scalar", nxT[:], u128(pb[j][:]), [f"pb{j}"], [nxTn])
                cur, curn, nx, nxn = nx, nxn, cur, curn
                curT, curTn, nxT, nxTn = nxT, nxTn, curT, curTn
                yield
            j = nxt("pb")
            for u in range(4):
                P.mm(pb[j][:, u * 128:(u + 1) * 128], curT[:, u, :], cur[:, u, 1, :], True, True, [curTn, curn], [f"pb{j}"])
            P.tt("vector", nx[:, :, 1, :], u128(pb[j][:]), cur[:, :, 1, :], ALU.add, [f"pb{j}", curn], [nxn])
            W6, W6n = nx, nxn
            j = nxt("pb")
            for u in range(4):
                P.mm(pb[j][:, u * 64:(u + 1) * 64], fn["NP"][:, u, 0:128], Vb[:, u, :], True, True, [FR("NP"), Vbn], [f"pb{j}"])
            P.cp("scalar", fn["NVb"][:].rearrange("p u x -> p (u x)"), pb[j][:, 0:256], [f"pb{j}"], [FR("NVb")])
            yield

            def mm_d(u, o0, n0, o1, n1):
                P.mm(o0, W6[:, u, 1, :], iv["MQ"][:, u, 128:256], True, True, [W6n, IR("MQ")], [n0])
                P.mm(o1, W6[:, u, 1, :], iv["Btok"][:, u, :], True, True, [W6n, IR("Btok")], [n1])
            j0, j1 = two_bank(mm_d)
            P.cp("scalar", fn["XW"][:, :, 0:128], u128(pb[j0][:]), [f"pb{j0}"], [FR("XW")])
            P.cp("vector", fn["XW"][:, :, 128:256], u128(pb[j1][:]), [f"pb{j1}"], [FR("XW")])
            yield

            def mm_f(u, o0, n0, o1, n1):
                P.mm(o0, iv["Atok"][:, u, :], fn["XW"][:, u, 0:128], True, True, [IR("Atok"), FR("XW")], [n0])
                P.mm(o1, iv["Atok"][:, u, :], fn["XW"][:, u, 128:256], True, True, [IR("Atok"), FR("XW")], [n1])
            j0, j1 = two_bank(mm_f)
            P.tt("vector", fn["GY"][:], u128(pb[j0][:]), AR[:, :, 1, :], ALU.add, [f"pb{j0}", ARn], [FR("GY")])
            P.tt("vector", fn["GS"][:], u128(pb[j1][:]), idb, ALU.add, [f"pb{j1}", "identb"], [FR("GS")])
            yield

        def finish(ti, oc, q, z):
            isctx, idx = RW_ORDER1[ti]
            tg = 16 if isctx else idx
            sf, sb_ = fin[q]
            F0 = lambda nm: f"f{q}0_{nm}"
            F1 = lambda nm: f"f{q}1_{nm}"
            Vb, Vbn, gam = Vbq[z], f"Vb{z}", gamq[z]
            SFR = ("Sf", oc)
            for u in range(4):
                yo = pf[:, u * 64:(u + 1) * 64]
                P.mm(yo, sf["NP"][:, u, 128:256], Vb[:, u, :], True, False, [F0("NP"), Vbn], ["pf"])
                P.mm(yo, sf["XW"][:, u, 0:128], sf["NVb"][:, u, :], False, False, [F0("XW"), F0("NVb")], ["pf"])
                P.mm(yo, sb_["NP"][:, u, 128:256], Vb[:, u, :], False, False, [F1("NP"), Vbn], ["pf"])
                P.mm(yo, sb_["XW"][:, u, 0:128], sb_["NVb"][:, u, :], False, False, [F1("XW"), F1("NVb")], ["pf"])
                P.mm(yo, sf["GY"][:, u, :], Sf[:, oc, :], False, True, [F0("GY"), SFR], ["pf"])
                so = pf[:, 256:320]
                P.mm(so, sf["Ktok"][:, u, :], Vb[:, u, :], True, False, [F0("Ktok"), Vbn], ["pf"])
                P.mm(so, sf["XW"][:, u, 128:256], sf["NVb"][:, u, :], False, False, [F0("XW"), F0("NVb")], ["pf"])
                P.mm(so, sf["GS"][:, u, :], Sf[:, oc, :], False, True, [F0("GS"), SFR], ["pf"])
                P.ts("vector", Sf[:, oc, :], so, gam[0][:, u:u + 1], None, ALU.mult, None, ["pf", f"gam{z}0"], [SFR])
                yield
            P.cp("vector", YPs[:].rearrange("p u x -> p (u x)"), pf[:, 0:256], ["pf"], ["YPs"])
            P.dma("sync", S["yp"][tg, oc], YPs[:].rearrange("p u x -> p (u x)"), reads=["YPs"], writes=[("yp", tg, oc)], sem="YPs")
            j = nxt("pb")
            for u in range(4):
                so = pb[j][:, u * 64:(u + 1) * 64]
                P.mm(so, sb_["Ktok"][:, u, :], Vb[:, u, :], True, False, [F1("Ktok"), Vbn], [f"pb{j}"])
                P.mm(so, sb_["XW"][:, u, 128:256], sb_["NVb"][:, u, :], False, True, [F1("XW"), F1("NVb")], [f"pb{j}"])
            P.cp("scalar", SAs[:].rearrange("p u x -> p (u x)"), pb[j][:, 0:256], [f"pb{j}"], ["SAs"])
            P.dma("sync", S["sadd"][tg, oc], SAs[:].rearrange("p u x -> p (u x)"), reads=["SAs"], writes=[("sadd", tg, oc)], sem="SAs")
            P.dma("sync", S["gyb"][tg, oc], sb_["GY"][:].rearrange("p u x -> p (u x)"), reads=[F1("GY")], writes=[("gyb", tg, oc)], sem=F1("GY"))
            P.dma("sync", S["gsb"][tg, oc], sb_["GS"][:].rearrange("p u x -> p (u x)"), reads=[F1("GS")], writes=[("gsb", tg, oc)], sem=F1("GS"))
            yield

        NT = len(RW_ORDER1)
        NJ = NT * 8
        done = {"prep": set(), "c0": set(), "c1": set(), "fin": set(), "tprep": set()}

        def stream_P():
            for ti in range(NT):
                yield ("tprep", ti, lambda ti=ti: (ti == 0 or ("prep", (ti - 1) * 8 + 7) in donef), lambda ti=ti: tprep(ti))
                for oc in range(8):
                    k = ti * 8 + oc
                    yield ("prep", k, lambda k=k: ((k < 2 or (("c0", k - 2) in donef and ("c1", k - 2) in donef)) and (k < 3 or ("fin", k - 3) in donef)),
                           lambda ti=ti, oc=oc, k=k: prep(ti, oc, k % 2, k % 3))

        def stream_C(d):
            for k in range(NJ):
                ti, oc = divmod(k, 8)
                yield (f"c{d}", k, lambda k=k: (("prep", k) in donef and (k < 2 or ("fin", k - 2) in donef)),
                       lambda ti=ti, oc=oc, k=k: chain(ti, oc, k % 2, d, k % 3))

        def stream_F():
            for k in range(NJ):
                ti, oc = divmod(k, 8)
                yield ("fin", k, lambda k=k: (("c0", k) in donef and ("c1", k) in donef),
                       lambda ti=ti, oc=oc, k=k: finish(ti, oc, k % 2, k % 3))

        donef = set()
        load_hh(0)
        streams = [stream_C(0), stream_C(1), stream_F(), stream_P()]
        cur = [None] * 4
        pend = [None] * 4
        alive = [True] * 4
        while any(alive):
            progressed = False
            for si in range(4):
                if not alive[si]:
                    continue
                if cur[si] is None:
                    if pend[si] is None:
                        try:
                            pend[si] = next(streams[si])
                        except StopIteration:
                            alive[si] = False
                            continue
                    kind, k, ready, mk = pend[si]
                    if not ready():
                        continue
                    cur[si] = (kind, k, mk())
                    pend[si] = None
                kind, k, gen = cur[si]
                try:
                    next(gen)
                    progressed = True
                except StopIteration:
                    donef.add((kind, k))
                    cur[si] = None
                    progressed = True
            assert progressed or not any(alive), "scheduler stuck"


def stage_rwkv1a(P, io, G, hp, S):
    vec, masks, identb, identf, bones, onesb, rmask = (G[k] for k in ("vec", "masks", "identb", "identf", "bones", "onesb", "rmask"))
    with P.phase("rwkv1a"):
        wr = P.sb([128, 8, 1024], BF16)
        wk = P.sb([128, 8, 1024], BF16)
        wv = P.sb([128, 8, 1024], BF16)
        for w, nm in ((wr, "rwkv_wr"), (wk, "rwkv_wk"), (wv, "rwkv_wv")):
            P.dma("gpsimd", w[:], fm(io[nm]), writes=[nm], sem=nm)
        lw1 = P.sb([128, 8, 128], BF16)
        la1 = P.sb([128, 8, 128], BF16)
        g1 = P.sb([128, 8, 128], BF16)
        for d in range(2):
            P.dma("gpsimd", lw1[:, :, d * 64:(d + 1) * 64], io["rwkv_w1"][d].rearrange("(c p) j -> p c j", p=128), writes=["lw1"], sem=f"lw1{d}")
            P.dma("gpsimd", la1[:, :, d * 64:(d + 1) * 64], io["rwkv_a1"][d].rearrange("(c p) j -> p c j", p=128), writes=["la1"], sem=f"la1{d}")
        P.dma("gpsimd", g1[:], io["rwkv_g1"].rearrange("(c p) j -> p c j", p=128), writes=["g1"], sem="g1")
        w2s = P.sb([128, 1024], BF16)
        a2s = P.sb([128, 1024], BF16)
        g2 = P.sb([128, 1024], BF16)
        P.dma("gpsimd", w2s[:], io["rwkv_w2"].rearrange("d j f -> (d j) f"), writes=["w2s"], sem="w2s")
        P.dma("gpsimd", a2s[:], io["rwkv_a2"].rearrange("d j f -> (d j) f"), writes=["a2s"], sem="a2s")
        P.dma("gpsimd", g2[:], io["rwkv_g2"], writes=["g2"], sem="g2")

        hh = P.sb([128, 8, 384], F32)
        xx = P.sb([128, 8, 256], F32)
        xr = P.sb([128, 8, 256], BF16)
        xk = P.sb([128, 8, 256], BF16)
        xv = P.sb([128, 8, 256], BF16)
        xrot = P.sb([128, 8, 256], BF16)
        lwt = P.sb([128, 256], BF16)
        lat = P.sb([128, 256], BF16)
        sg = P.sb([128, 256], BF16)
        NSET = 3
        bufs = []
        for w_ in range(NSET):
            B_ = {"t": {}}
            for nm in ("r", "k", "sw0", "sw1", "ag0", "ag1", "kq", "lnv", "rs", "kkn", "fac", "kd0", "kd1", "b0", "b1",
                       "L", "Lx", "Lb", "E1", "E2", "E3", "ks"):
                B_["t"][nm] = P.sb([128, 256], F32, f"t{w_}_" + nm)
            B_["sqb"] = P.sb([128, 256], BF16)
            B_["RK"] = P.sb([128, 4, 2, 64], BF16)
            B_["VTbd"] = P.sb([128, 4, 128], F32)
            B_["GTbd"] = P.sb([128, 4, 128], F32)
            B_["Vf"] = P.sb([128, 4, 64], F32)
            B_["Gf"] = P.sb([128, 4, 64], F32)
            B_["ops"] = P.sb([128, 2, 4, 256], BF16)
            B_["vb"] = P.sb([128, 4, 64], BF16)
            B_["gam"] = P.sb([128, 2, 4], F32)
            bufs.append(B_)
            P.memset("gpsimd", B_["RK"][:], 0.0, [("RK", w_)])
            P.memset("gpsimd", B_["VTbd"][:], 0.0, [("VTbd", w_)])
            P.memset("gpsimd", B_["GTbd"][:], 0.0, [("GTbd", w_)])
        P.ns_set = frozenset(["r", "k", "sw0", "sw1", "ag0", "ag1", "kq", "lnv", "rs", "kkn", "fac", "kd0", "kd1", "b0", "b1",
                              "L", "Lx", "Lb", "E1", "E2", "E3", "ks", "sqb", "RK", "VTbd", "GTbd", "Vf", "Gf", "ops_st", "vb_st", "gam_st"])
        gamb_t = P.sb([128, 8, 4], F32)
        bon_t = P.sb([128, 8, 4], F32)
        ppt = [P.ps([128, 512], F32) for _ in range(4)]
        pp = [t_[:, 0:256] for t_ in ppt]
        pb = [P.ps([128, 512], F32) for _ in range(4)]
        cnt = {"pp": 0, "pb": 0}
        nmod = {"pp": 4, "pb": 4}

        def nxt(kind):
            i = cnt[kind] % nmod[kind]
            cnt[kind] += 1
            return i

        def v3(ap):
            return ap.rearrange("p (u s) -> p u s", s=64)

        def u128(ap):
            return ap.rearrange("p (u x) -> p u x", x=128)

        def load_hh(ti):
            isctx, idx = RW_ORDER1[ti]
            off = 4288 if isctx else 64 + 256 * idx
            P.dma("sync", hh[:], fm(hp[:, off - 64: off + 320]), writes=["hh"], sem="hh")

        def proj8(w_cols_fn, xb, bn, extra_r):
            i = nxt("pp")
            for c in range(8):
                P.mm(pp[i], w_cols_fn(c), xb[:, c, :], c == 0, c == 7, [(bn, c)] + extra_r, [f"pp{i}"])
            return i

        def tprep(ti):
            isctx, idx = RW_ORDER1[ti]
            hc = hh[:, :, 64:320]
            XXW = [("xx", c) for c in range(8)]
            if not isctx:
                h4 = hh[:, :, 64:320].rearrange("p c (r w) -> p c r w", w=64)
                x4 = xx[:].rearrange("p c (r w) -> p c r w", w=64)
                P.tt("vector", x4[:, 0:2, :, 1:64], h4[:, 0:2, :, 0:63], h4[:, 0:2, :, 1:64], ALU.subtract, ["hh"], XXW[0:2])
                P.ts("gpsimd", x4[:, 0:2, :, 0:1], h4[:, 0:2, :, 0:1], -1.0, 0.0, ALU.mult, ALU.add, ["hh"], [("xxe", 0)])
                P.tt("vector", x4[:, 2:4, :, 0:63], h4[:, 2:4, :, 1:64], h4[:, 2:4, :, 0:63], ALU.subtract, ["hh"], XXW[2:4])
                P.ts("gpsimd", x4[:, 2:4, :, 63:64], h4[:, 2:4, :, 63:64], -1.0, 0.0, ALU.mult, ALU.add, ["hh"], [("xxe", 1)])
                P.tt("gpsimd", xx[:, 4:6, :], hh[:, 4:6, 0:256], hh[:, 4:6, 64:320], ALU.subtract, ["hh"], XXW[4:6])
                P.tt("gpsimd", xx[:, 6:8, :], hh[:, 6:8, 128:384], hh[:, 6:8, 64:320], ALU.subtract, ["hh"], XXW[6:8])
            else:
                P.tt("vector", xx[:, 0:4, :], hh[:, 0:4, 63:319], hh[:, 0:4, 64:320], ALU.subtract, ["hh"], XXW[0:4] + [("xxe", 0)])
                P.tt("gpsimd", xx[:, 4:8, :], hh[:, 4:8, 65:321], hh[:, 4:8, 64:320], ALU.subtract, ["hh"], XXW[4:8] + [("xxe", 1)])
            yield

            def mk_xj(j, buf, bn):
                for c in range(8):
                    P.stt(buf[:, c, :], xx[:, c, :], vec[:, 5 + j, c:c + 1], hc[:, c, :], ALU.mult, ALU.add,
                          [("xx", c), ("xxe", 0), ("xxe", 1), "hh", "vec"], [(bn, c)])

            mk_xj(1, xrot, "xrot")
            yield
            i = proj8(lambda c: lw1[:, c, :], xrot, "xrot", ["lw1"])
            P.act(lwt[:], pp[i], AF.Tanh, [f"pp{i}"], ["lwt"])
            yield
            mk_xj(4, xrot, "xrot")
            yield
            i = proj8(lambda c: la1[:, c, :], xrot, "xrot", ["la1"])
            P.cp("scalar", lat[:], pp[i], [f"pp{i}"], ["lat"])
            yield
            mk_xj(5, xrot, "xrot")
            yield
            i = proj8(lambda c: g1[:, c, :], xrot, "xrot", ["g1"])
            P.act(sg[:], pp[i], AF.Sigmoid, [f"pp{i}"], ["sg"])
            yield
            mk_xj(0, xr, "xr")
            yield
            mk_xj(2, xk, "xk")
            yield
            mk_xj(3, xv, "xv")
            if ti + 1 < len(RW_ORDER1):
                load_hh(ti + 1)
            yield

        def prep(ti, oc, w):
            isctx, idx = RW_ORDER1[ti]
            tg = 16 if isctx else idx
            cs = slice(oc * 128, (oc + 1) * 128)
            B_ = bufs[w]
            t, sqb, RK, VTbd, GTbd, Vf, Gf = B_["t"], B_["sqb"], B_["RK"], B_["VTbd"], B_["GTbd"], B_["Vf"], B_["Gf"]
            ops_st, vb_st, gam_st = B_["ops"], B_["vb"], B_["gam"]
            i = proj8(lambda c: wr[:, c, cs], xr, "xr", ["rwkv_wr"])
            P.cp("scalar", t["r"][:], pp[i], [f"pp{i}"], ["r"])
            i = proj8(lambda c: wk[:, c, cs], xk, "xk", ["rwkv_wk"])
            P.cp("scalar", t["k"][:], pp[i], [f"pp{i}"], ["k"])
            i = proj8(lambda c: wv[:, c, cs], xv, "xv", ["rwkv_wv"])
            vt4 = VTbd[:].rearrange("p u (h s) -> p u h s", h=2)
            for h2 in range(2):
                sl = slice(h2 * 64, (h2 + 1) * 64)
                P.cp("scalar", vt4[sl, :, h2, :], v3(pp[i][sl, :]), [f"pp{i}"], ["VTbd"])
            i = nxt("pp")
            P.mm(pp[i], g2[:, cs], sg[:], True, True, ["g2", "sg"], [f"pp{i}"])
            gt4 = GTbd[:].rearrange("p u (h s) -> p u h s", h=2)
            for h2 in range(2):
                sl = slice(h2 * 64, (h2 + 1) * 64)
                P.cp("scalar", gt4[sl, :, h2, :], v3(pp[i][sl, :]), [f"pp{i}"], ["GTbd"])
            yield
            j = nxt("pb")
            for u in range(4):
                P.tr(pb[j][:, u * 128:(u + 1) * 128], VTbd[:, u, :], identf[:], ["VTbd", "identf"], [f"pb{j}"])
            pv = u128(pb[j][:])
            for h2 in range(2):
                sl = slice(h2 * 64, (h2 + 1) * 64)
                P.cp("scalar", Vf[sl, :, :], pv[sl, :, h2 * 64:(h2 + 1) * 64], [f"pb{j}"], ["Vf"])
            P.cp("gpsimd", vb_st[:], Vf[:], ["Vf"], ["vb_st"])
            P.dma("sync", S["vst"][tg, oc].rearrange("p (u s) -> p u s", s=64), Vf[:], reads=["Vf"], writes=[("vst", tg, oc)], sem="Vf")
            j = nxt("pb")
            for u in range(4):
                P.tr(pb[j][:, u * 128:(u + 1) * 128], GTbd[:, u, :], identf[:], ["GTbd", "identf"], [f"pb{j}"])
            pv = u128(pb[j][:])
            for h2 in range(2):
                sl = slice(h2 * 64, (h2 + 1) * 64)
                P.cp("scalar", Gf[sl, :, :], pv[sl, :, h2 * 64:(h2 + 1) * 64], [f"pb{j}"], ["Gf"])
            P.dma("sync", S["gst"][tg, oc].rearrange("p (u s) -> p u s", s=64), Gf[:], reads=["Gf"], writes=[("gst", tg, oc)], sem="Gf")
            yield
            for d in range(2):
                dl = slice(d * 64, (d + 1) * 64)
                i = nxt("pp")
                P.mm(pp[i], w2s[dl, cs], lwt[dl, :], True, True, ["w2s", "lwt"], [f"pp{i}"])
                P.act(t[f"sw{d}"][:], pp[i], AF.Sigmoid, [f"pp{i}", "vec"], [f"sw{d}"], bias=vec[:, 11 + d, oc:oc + 1])
                i = nxt("pp")
                P.mm(pp[i], a2s[dl, cs], lat[dl, :], True, True, ["a2s", "lat"], [f"pp{i}"])
                P.act(t[f"ag{d}"][:], pp[i], AF.Sigmoid, [f"pp{i}", "vec"], [f"ag{d}"], bias=vec[:, 13 + d, oc:oc + 1])
            yield
            P.ts("vector", t["kq"][:], t["k"][:], vec[:, 15, oc:oc + 1], None, ALU.mult, None, ["k", "vec"], ["kq"])
            P.act(sqb[:], t["kq"][:], AF.Square, ["kq"], ["sqb"])
            i = nxt("pp")
            P.mm(pp[i], bones[:], sqb[:], True, True, ["bones", "sqb"], [f"pp{i}"])
            P.act(t["lnv"][:], pp[i], AF.Ln, [f"pp{i}"], ["lnv"], bias=1e-12)
            P.act(t["rs"][:], t["lnv"][:], AF.Exp, ["lnv"], ["rs"], scale=-0.5)
            P.tt("gpsimd", t["kkn"][:], t["kq"][:], t["rs"][:], ALU.mult, ["kq", "rs"], ["kkn"])
            for d in range(2):
                sw, ag, kd, bb = t[f"sw{d}"], t[f"ag{d}"], t[f"kd{d}"], t[f"b{d}"]
                EE = "gpsimd" if d == 0 else "vector"
                P.ts(EE, t["fac"][:], ag[:], vec[:, 16, oc:oc + 1], vec[:, 17, oc:oc + 1], ALU.mult, ALU.add, [f"ag{d}", "vec"], ["fac"])
                P.tt(EE, kd[:], t["k"][:], t["fac"][:], ALU.mult, ["k", "fac"], [f"kd{d}"])
                P.tt(EE, bb[:], t["kkn"][:], ag[:], ALU.mult, ["kkn", f"ag{d}"], [f"b{d}"])
                P.op("vector", lambda e, sw=sw: e.tensor_tensor_scan(out=t["L"][:], data0=rmask[:], data1=sw[:], initial=0.0,
                                                                      op0=ALU.mult, op1=ALU.add), [f"sw{d}", "rmask"], ["L"])
                L3 = v3(t["L"][:])
                if d == 0:
                    P.tt(EE, t["Lx"][:], t["L"][:], sw[:], ALU.subtract, ["L", f"sw{d}"], ["Lx"])
                    Li, Lin = t["L"], "L"
                else:
                    P.tt(EE, v3(t["Lx"][:]), L3[:, :, 63:64].broadcast_to([128, 4, 64]), L3, ALU.subtract, ["L"], ["Lx"])
                    P.tt(EE, t["Lb"][:], t["Lx"][:], sw[:], ALU.add, ["Lx", f"sw{d}"], ["Lb"])
                    Li, Lin = t["Lb"], "Lb"
                P.act(t["E1"][:], Li[:], AF.Exp, [Lin], ["E1"], scale=-C0)
                P.act(t["E3"][:], Li[:], AF.Exp, [Lin], ["E3"], scale=C0)
                P.act(t["E2"][:], t["Lx"][:], AF.Exp, ["Lx"], ["E2"], scale=-C0)
                P.stt(ops_st[:, d, 0, :], t["kkn"][:], -1.0, t["E2"][:], ALU.mult, ALU.mult, ["kkn", "E2"], ["ops_st"])
                P.tt("gpsimd", ops_st[:, d, 1, :], t["r"][:], t["E1"][:], ALU.mult, ["r", "E1"], ["ops_st"])
                P.tt(EE, ops_st[:, d, 2, :], kd[:], t["E3"][:], ALU.mult, [f"kd{d}", "E3"], ["ops_st"])
                P.tt(EE, ops_st[:, d, 3, :], bb[:], t["E3"][:], ALU.mult, [f"b{d}", "E3"], ["ops_st"])
                E13 = v3(t["E1"][:])
                gsrc = E13[:, :, 63] if d == 0 else E13[:, :, 0]
                P.cp("vector", gam_st[:, d, :], gsrc, ["E1"], ["gam_st"])
                if d == 1:
                    P.cp("gpsimd", gamb_t[:, oc, :], gam_st[:, 1, :], ["gam_st"], ["gamb_t"])
                yield
            P.tt("gpsimd", t["ks"][:], t["kd0"][:], t["kd1"][:], ALU.add, ["kd0", "kd1"], ["ks"])
            for h2 in range(2):
                sl = slice(h2 * 64, (h2 + 1) * 64)
                P.stt(RK[sl, :, h2, :], v3(t["r"][sl, :]), vec[sl, 18, oc:oc + 1], v3(t["ks"][sl, :]), ALU.mult, ALU.mult, ["r", "ks", "vec"], ["RK"])
            i = nxt("pp")
            for u in range(4):
                P.mm(pp[i][:, u:u + 1], RK[:, u, :, :].rearrange("p h s -> p (h s)"), onesb[:, 0:1], True, True, ["RK", "onesb"], [f"pp{i}"])
            P.cp("scalar", bon_t[:, oc, :], pp[i][:, 0:4], [f"pp{i}"], ["bon_t"])
            P.dma("sync", S["ops"][tg, oc], ops_st[:].rearrange("p d x n -> p (d x n)"), reads=["ops_st"], writes=[("ops", tg, oc)], sem="ops_st")
            P.dma("sync", S["vb"][tg, oc], vb_st[:].rearrange("p u s -> p (u s)"), reads=["vb_st"], writes=[("vb", tg, oc)], sem="vb_st")
            P.dma("sync", S["gam"][tg, oc], gam_st[:].rearrange("p d u -> p (d u)"), reads=["gam_st"], writes=[("gam", tg, oc)], sem="gam_st")
            yield


        NT = len(RW_ORDER1)
        load_hh(0)
        for ti in range(NT):
            isctx, idx = RW_ORDER1[ti]
            tg = 16 if isctx else idx
            for _ in tprep(ti):
                pass
            jobs = [(oc % NSET, prep(ti, oc, oc % NSET)) for oc in range(8)]
            active = []
            while jobs or active:
                while jobs and len(active) < NSET:
                    active.append(jobs.pop(0))
                for item in list(active):
                    P.ns = item[0]
                    try:
                        next(item[1])
                    except StopIteration:
                        active.remove(item)
                    P.ns = None
            P.dma("sync", S["gamb"][tg], gamb_t[:].rearrange("p a b -> p (a b)"), reads=["gamb_t"], writes=[("gamb", tg)], sem="gamb_t")
            P.dma("sync", S["bon"][tg], bon_t[:].rearrange("p a b -> p (a b)"), reads=["bon_t"], writes=[("bon", tg)], sem="bon_t")
        P.ns_set = frozenset()


def stage_rwkv1b(P, io, G, S):
    vec, masks, identb, identf, bones, onesb, rmask = (G[k] for k in ("vec", "masks", "identb", "identf", "bones", "onesb", "rmask"))
    with P.phase("rwkv1b"):
        YPs = P.sb([128, 4, 64], F32)
        SAs = P.sb([128, 4, 64], F32)
        Sf = P.sb([128, 8, 64], BF16)
        ARq = [[P.sb([128, 4, 2, 128], BF16, f"AR{q}{d}") for d in range(2)] for q in range(3)]
        KTq = [[P.sb([128, 4, 128], BF16, f"KT{q}{d}") for d in range(2)] for q in range(3)]
        BTq = [[P.sb([128, 4, 128], BF16, f"BT{q}{d}") for d in range(2)] for q in range(3)]
        stg = [P.sb([128, 2, 4, 256], BF16, f"stg{q}") for q in range(3)]
        Vbq = [P.sb([128, 4, 64], BF16, f"Vb{q}") for q in range(4)]
        gamq = [P.sb([128, 2, 4], F32, f"gam{q}") for q in range(4)]
        inv2 = []
        for q in range(2):
            row = []
            for d in range(2):
                st = {}
                for nm, shp in (("Atok", [128, 4, 128]), ("Btok", [128, 4, 128]), ("MQ", [128, 4, 256]), ("MWa", [128, 4, 2, 128]),
                                ("MWb", [128, 4, 2, 128]), ("MTa", [128, 4, 128]), ("MTb", [128, 4, 128])):
                    st[nm] = P.sb(shp, BF16, f"i{q}{d}_{nm}")
                row.append(st)
            inv2.append(row)
        fin = []
        for q in range(2):
            row = []
            for d in range(2):
                st = {}
                for nm, shp in (("Ktok", [128, 4, 128]), ("NP", [128, 4, 256]), ("XW", [128, 4, 256]), ("NVb", [128, 4, 64]),
                                ("GY", [128, 4, 128]), ("GS", [128, 4, 128])):
                    st[nm] = P.sb(shp, BF16, f"f{q}{d}_{nm}")
                row.append(st)
            fin.append(row)
        pf = P.ps([128, 512], F32)
        pb = [P.ps([128, 512], F32) for _ in range(7)]
        cnt = {"pb": 0}
        nmod = {"pb": 7}

        def nxt(kind):
            i = cnt[kind] % nmod[kind]
            cnt[kind] += 1
            return i

        for q in range(3):
            for d in range(2):
                P.memset("gpsimd", ARq[q][d][:], 0.0, [f"AR{q}{d}"])
                P.memset("gpsimd", KTq[q][d][:], 0.0, [f"KT{q}{d}"])
                P.memset("gpsimd", BTq[q][d][:], 0.0, [f"BT{q}{d}"])
        P.memset("gpsimd", Sf[:], 0.0, [("Sf", p) for p in range(8)])

        def v3(ap):
            return ap.rearrange("p (u s) -> p u s", s=64)

        def u128(ap):
            return ap.rearrange("p (u x) -> p u x", x=128)

        def loadjob(ti, oc, a, z):
            isctx, idx = RW_ORDER1[ti]
            tg = 16 if isctx else idx
            sg_ = stg[a]
            P.dma("sync", sg_[:].rearrange("p d x n -> p (d x n)"), S["ops"][tg, oc], writes=[f"stg{a}"], sem=f"stg{a}")
            P.dma("sync", Vbq[z][:].rearrange("p u s -> p (u s)"), S["vb"][tg, oc], writes=[f"Vb{z}"], sem=f"Vb{z}")
            P.dma("sync", gamq[z][:].rearrange("p d u -> p (d u)"), S["gam"][tg, oc], writes=[f"gam{z}"], sem=f"gam{z}")
            yield
            for d in range(2):
                ar5 = ARq[a][d][:].rearrange("p u a (h s) -> p u a h s", h=2)
                kt4 = KTq[a][d][:].rearrange("p u (h s) -> p u h s", h=2)
                bt4 = BTq[a][d][:].rearrange("p u (h s) -> p u h s", h=2)
                for h2 in range(2):
                    sl = slice(h2 * 64, (h2 + 1) * 64)
                    P.cp("gpsimd", ar5[sl, :, 0, h2, :], v3(sg_[sl, d, 0, :]), [f"stg{a}"], [f"AR{a}{d}"])
                    P.cp("gpsimd", ar5[sl, :, 1, h2, :], v3(sg_[sl, d, 1, :]), [f"stg{a}"], [f"AR{a}{d}"])
                    P.cp("gpsimd", kt4[sl, :, h2, :], v3(sg_[sl, d, 2, :]), [f"stg{a}"], [f"KT{a}{d}"])
                    P.cp("gpsimd", bt4[sl, :, h2, :], v3(sg_[sl, d, 3, :]), [f"stg{a}"], [f"BT{a}{d}"])
                    yield

        def chain(ti, oc, q, d, z, a):
            AR, KT, BT, Vb = ARq[a][d], KTq[a][d], BTq[a][d], Vbq[z]
            ARn, KTn, BTn, Vbn = f"AR{a}{d}", f"KT{a}{d}", f"BT{a}{d}", f"Vb{z}"
            iv, fn = inv2[q][d], fin[q][d]
            IR = lambda nm: f"i{q}{d}_{nm}"
            FR = lambda nm: f"f{q}{d}_{nm}"
            mS, mC = (0, 2) if d == 0 else (2, 0)
            mSI = masks[:, mS:mS + 2, :].rearrange("p a b -> p (a b)").unsqueeze(1).broadcast_to([128, 4, 256])
            mCb = masks[:, mC, :].unsqueeze(1).broadcast_to([128, 4, 128])
            idb = identb[:].unsqueeze(1).broadcast_to([128, 4, 128])
            for src, srcn, dst, dstn in ((AR[:, :, 0, :], ARn, iv["Atok"], IR("Atok")), (BT[:], BTn, iv["Btok"], IR("Btok")),
                                         (KT[:], KTn, fn["Ktok"], FR("Ktok"))):
                j = nxt("pb")
                pbt = pb[j][:].bitcast(BF16)
                for u in range(4):
                    P.tr(pbt[:, u * 128:(u + 1) * 128], src[:, u, :], identb[:], [srcn, "identb"], [f"pb{j}"])
                P.cp("scalar", dst[:].rearrange("p u x -> p (u x)"), pbt[:, 0:512], [f"pb{j}"], [dstn])
            mSb = masks[:, mS, :].unsqueeze(1).broadcast_to([128, 4, 128])
            mIb = masks[:, mS + 1, :].unsqueeze(1).broadcast_to([128, 4, 128])

            def two_bank(mm_fn):
                j0, j1 = nxt("pb"), nxt("pb")
                for u in range(4):
                    mm_fn(u, pb[j0][:, u * 128:(u + 1) * 128], f"pb{j0}", pb[j1][:, u * 128:(u + 1) * 128], f"pb{j1}")
                return j0, j1

            for lhs, lhsn, dst, dstn in ((BT, BTn, iv["MQ"], IR("MQ")), (KT, KTn, fn["NP"], FR("NP"))):
                def mm_ab(u, o0, n0, o1, n1, lhs=lhs, lhsn=lhsn):
                    P.mm(o0, lhs[:, u, :], AR[:, u, 0, :], True, True, [lhsn, ARn], [n0])
                    P.mm(o1, lhs[:, u, :], AR[:, u, 1, :], True, True, [lhsn, ARn], [n1])
                j0, j1 = two_bank(mm_ab)
                P.tt("vector", dst[:, :, 0:128], u128(pb[j0][:]), mSb, ALU.mult, [f"pb{j0}", "masks"], [dstn])
                P.tt("vector", dst[:, :, 128:256], u128(pb[j1][:]), mIb, ALU.mult, [f"pb{j1}", "masks"], [dstn])
            j = nxt("pb")
            for u in range(4):
                P.mm(pb[j][:, u * 128:(u + 1) * 128], AR[:, u, 0, :], BT[:, u, :], True, True, [ARn, BTn], [f"pb{j}"])
            cur, curn, nx, nxn = iv["MWa"], IR("MWa"), iv["MWb"], IR("MWb")
            P.tt("vector", cur[:, :, 0, :], u128(pb[j][:]), mCb, ALU.mult, [f"pb{j}", "masks"], [curn])
            yield
            j = nxt("pb")
            for u in range(4):
                P.mm(pb[j][:, u * 128:(u + 1) * 128], iv["MQ"][:, u, 0:128], cur[:, u, 0, :], True, True, [IR("MQ"), curn], [f"pb{j}"])
            P.cp("scalar", nx[:, :, 0, :], u128(pb[j][:]), [f"pb{j}"], [nxn])
            P.tt("gpsimd", nx[:, :, 1, :], cur[:, :, 0, :], idb, ALU.add, [curn, "identb"], [nxn])
            j = nxt("pb")
            for u in range(4):
                P.mm(pb[j][:, u * 128:(u + 1) * 128], cur[:, u, 0, :], iv["MQ"][:, u, 0:128], True, True, [IR("MQ"), curn], [f"pb{j}"])
            curT, curTn, nxT, nxTn = iv["MTa"], IR("MTa"), iv["MTb"], IR("MTb")
            P.cp("scalar", curT[:], u128(pb[j][:]), [f"pb{j}"], [curTn])
            cur, curn, nx, nxn = nx, nxn, cur, curn
            yield
            for lev in range(1, 5):
                def mm_lev(u, o0, n0, o1, n1, cur=cur, curn=curn, curT=curT, curTn=curTn):
                    P.mm(o0, curT[:, u, :], cur[:, u, 0, :], True, True, [curTn, curn], [n0])
                    P.mm(o1, curT[:, u, :], cur[:, u, 1, :], True, True, [curTn, curn], [n1])
                j0, j1 = two_bank(mm_lev)
                P.cp("scalar", nx[:, :, 0, :], u128(pb[j0][:]), [f"pb{j0}"], [nxn])
                P.tt("vector", nx[:, :, 1, :], u128(pb[j1][:]), cur[:, :, 1, :], ALU.add, [f"pb{j1}", curn], [nxn])
                j = nxt("pb")
                for u in range(4):
                    P.mm(pb[j][:, u * 128:(u + 1) * 128], cur[:, u, 0, :], curT[:, u, :], True, True, [curn, curTn], [f"pb{j}"])
                P.cp("scalar", nxT[:], u128(pb[j][:]), [f"pb{j}"], [nxTn])
                cur, curn, nx, nxn = nx, nxn, cur, curn
                curT, curTn, nxT, nxTn = nxT, nxTn, curT, curTn
                yield
            j = nxt("pb")
            for u in range(4):
                P.mm(pb[j][:, u * 128:(u + 1) * 128], curT[:, u, :], cur[:, u, 1, :], True, True, [curTn, curn], [f"pb{j}"])
            P.tt("vector", nx[:, :, 1, :], u128(pb[j][:]), cur[:, :, 1, :], ALU.add, [f"pb{j}", curn], [nxn])
            W6, W6n = nx, nxn
            j = nxt("pb")
            for u in range(4):
                P.mm(pb[j][:, u * 64:(u + 1) * 64], fn["NP"][:, u, 0:128], Vb[:, u, :], True, True, [FR("NP"), Vbn], [f"pb{j}"])
            P.cp("scalar", fn["NVb"][:].rearrange("p u x -> p (u x)"), pb[j][:, 0:256], [f"pb{j}"], [FR("NVb")])
            yield

            def mm_d(u, o0, n0, o1, n1):
                P.mm(o0, W6[:, u, 1, :], iv["MQ"][:, u, 128:256], True, True, [W6n, IR("MQ")], [n0])
                P.mm(o1, W6[:, u, 1, :], iv["Btok"][:, u, :], True, True, [W6n, IR("Btok")], [n1])
            j0, j1 = two_bank(mm_d)
            P.cp("scalar", fn["XW"][:, :, 0:128], u128(pb[j0][:]), [f"pb{j0}"], [FR("XW")])
            P.cp("vector", fn["XW"][:, :, 128:256], u128(pb[j1][:]), [f"pb{j1}"], [FR("XW")])
            yield

            def mm_f(u, o0, n0, o1, n1):
                P.mm(o0, iv["Atok"][:, u, :], fn["XW"][:, u, 0:128], True, True, [IR("Atok"), FR("XW")], [n0])
                P.mm(o1, iv["Atok"][:, u, :], fn["XW"][:, u, 128:256], True, True, [IR("Atok"), FR("XW")], [n1])
            j0, j1 = two_bank(mm_f)
            P.tt("vector", fn["GY"][:], u128(pb[j0][:]), AR[:, :, 1, :], ALU.add, [f"pb{j0}", ARn], [FR("GY")])
            P.tt("vector", fn["GS"][:], u128(pb[j1][:]), idb, ALU.add, [f"pb{j1}", "identb"], [FR("GS")])
            yield

        def finish(ti, oc, q, z):
            isctx, idx = RW_ORDER1[ti]
            tg = 16 if isctx else idx
            sf, sb_ = fin[q]
            F0 = lambda nm: f"f{q}0_{nm}"
            F1 = lambda nm: f"f{q}1_{nm}"
            Vb, Vbn, gamz = Vbq[z], f"Vb{z}", gamq[z]
            SFR = ("Sf", oc)
            for u in range(4):
                yo = pf[:, u * 64:(u + 1) * 64]
                P.mm(yo, sf["NP"][:, u, 128:256], Vb[:, u, :], True, False, [F0("NP"), Vbn], ["pf"])
                P.mm(yo, sf["XW"][:, u, 0:128], sf["NVb"][:, u, :], False, False, [F0("XW"), F0("NVb")], ["pf"])
                P.mm(yo, sb_["NP"][:, u, 128:256], Vb[:, u, :], False, False, [F1("NP"), Vbn], ["pf"])
                P.mm(yo, sb_["XW"][:, u, 0:128], sb_["NVb"][:, u, :], False, False, [F1("XW"), F1("NVb")], ["pf"])
                P.mm(yo, sf["GY"][:, u, :], Sf[:, oc, :], False, True, [F0("GY"), SFR], ["pf"])
                so = pf[:, 256:320]
                P.mm(so, sf["Ktok"][:, u, :], Vb[:, u, :], True, False, [F0("Ktok"), Vbn], ["pf"])
                P.mm(so, sf["XW"][:, u, 128:256], sf["NVb"][:, u, :], False, False, [F0("XW"), F0("NVb")], ["pf"])
                P.mm(so, sf["GS"][:, u, :], Sf[:, oc, :], False, True, [F0("GS"), SFR], ["pf"])
                P.ts("vector", Sf[:, oc, :], so, gamz[:, 0, u:u + 1], None, ALU.mult, None, ["pf", f"gam{z}"], [SFR])
                yield
            P.cp("vector", YPs[:].rearrange("p u x -> p (u x)"), pf[:, 0:256], ["pf"], ["YPs"])
            P.dma("sync", S["yp"][tg, oc], YPs[:].rearrange("p u x -> p (u x)"), reads=["YPs"], writes=[("yp", tg, oc)], sem="YPs")
            j = nxt("pb")
            for u in range(4):
                so = pb[j][:, u * 64:(u + 1) * 64]
                P.mm(so, sb_["Ktok"][:, u, :], Vb[:, u, :], True, False, [F1("Ktok"), Vbn], [f"pb{j}"])
                P.mm(so, sb_["XW"][:, u, 128:256], sb_["NVb"][:, u, :], False, True, [F1("XW"), F1("NVb")], [f"pb{j}"])
            P.cp("scalar", SAs[:].rearrange("p u x -> p (u x)"), pb[j][:, 0:256], [f"pb{j}"], ["SAs"])
            P.dma("sync", S["sadd"][tg, oc], SAs[:].rearrange("p u x -> p (u x)"), reads=["SAs"], writes=[("sadd", tg, oc)], sem="SAs")
            P.dma("sync", S["gyb"][tg, oc], sb_["GY"][:].rearrange("p u x -> p (u x)"), reads=[F1("GY")], writes=[("gyb", tg, oc)], sem=F1("GY"))
            P.dma("sync", S["gsb"][tg, oc], sb_["GS"][:].rearrange("p u x -> p (u x)"), reads=[F1("GS")], writes=[("gsb", tg, oc)], sem=F1("GS"))
            yield


        NT = len(RW_ORDER1)
        NJ = NT * 8
        donef = set()

        def stream_L():
            for k in range(NJ):
                ti, oc = divmod(k, 8)
                yield ("load", k, lambda k=k: ((k < 3 or (("c0", k - 3) in donef and ("c1", k - 3) in donef)) and (k < 4 or ("fin", k - 4) in donef)),
                       lambda ti=ti, oc=oc, k=k: loadjob(ti, oc, k % 3, k % 4))

        def stream_C(d, par):
            for k in range(par, NJ, 2):
                ti, oc = divmod(k, 8)
                yield (f"c{d}", k, lambda k=k: (("load", k) in donef and (k < 2 or ("fin", k - 2) in donef)),
                       lambda ti=ti, oc=oc, k=k: chain(ti, oc, k % 2, d, k % 4, k % 3))

        def stream_F():
            for k in range(NJ):
                ti, oc = divmod(k, 8)
                yield ("fin", k, lambda k=k: (("c0", k) in donef and ("c1", k) in donef),
                       lambda ti=ti, oc=oc, k=k: finish(ti, oc, k % 2, k % 4))

        streams = [stream_L(), stream_C(0, 0), stream_C(1, 0), stream_C(0, 1), stream_C(1, 1), stream_F()]
        NS_ = len(streams)
        cur = [None] * NS_
        pend = [None] * NS_
        alive = [True] * NS_
        while any(alive):
            progressed = False
            for si in range(NS_):
                if not alive[si]:
                    continue
                if cur[si] is None:
                    if pend[si] is None:
                        try:
                            pend[si] = next(streams[si])
                        except StopIteration:
                            alive[si] = False
                            continue
                    kind, k, ready, mk = pend[si]
                    if not ready():
                        continue
                    cur[si] = (kind, k, mk())
                    pend[si] = None
                kind, k, gen = cur[si]
                try:
                    next(gen)
                    progressed = True
                except StopIteration:
                    donef.add((kind, k))
                    cur[si] = None
                    progressed = True
            assert progressed or not any(alive), "scheduler stuck"


def stage_rwkv2(P, io, G, S, src, xa):
    vec, identb = G["vec"], G["identb"]
    GN_EPS = 64e-5
    with P.phase("rwkv2"):
        wo = P.sb([64, 16, 1024], BF16)
        P.dma("gpsimd", wo[:], io["rwkv_wo"].rearrange("(h v) f -> v h f", v=64), writes=["wo"], sem="wo")
        lnw = P.sb([128, 8, 64], F32)
        lnb = P.sb([128, 8, 64], F32)
        P.dma("sync", lnw[:], io["lnw_st"], writes=["lnw"], sem="lnw")
        P.dma("sync", lnb[:], io["lnb_st"], writes=["lnb"], sem="lnb")
        big = {}
        for nm in ("yp", "sadd", "vst", "gst"):
            big[nm] = [P.sb([128, 8, 256], F32, f"l_{nm}{b}") for b in range(2)]
        for nm in ("gyb", "gsb"):
            big[nm] = [P.sb([128, 8, 512], BF16, f"l_{nm}{b}") for b in range(2)]
        gamb = [P.sb([128, 8, 4], F32) for _ in range(2)]
        bon = [P.sb([128, 8, 4], F32) for _ in range(2)]
        xt = [P.sb([128, 8, 256], F32) for _ in range(2)]
        Sb = P.sb([128, 8, 64], BF16)
        ysb2 = [P.sb([128, 8, 64], F32) for _ in range(2)]
        ysq2 = [P.sb([128, 8, 64], F32) for _ in range(2)]
        tmpS = P.sb([128, 8, 64], F32)
        yn2 = [P.sb([128, 8, 64], F32) for _ in range(2)]
        bv2 = [P.sb([128, 8, 64], F32) for _ in range(2)]
        ob2 = [P.sb([128, 8, 64], BF16) for _ in range(2)]
        st2 = [{nm: P.sb([128, 8], F32, f"g{k_}_" + nm) for nm in ("s1", "s2", "mean", "msq", "var", "lnv", "rstd")} for k_ in range(2)]
        OT = P.sb([64, 16, 256], BF16)
        py = [P.ps([128, 512], F32) for _ in range(2)]
        pS = P.ps([128, 512], F32)
        ptr = P.ps([128, 1024], F32)
        pw = [P.ps([128, 512], F32) for _ in range(2)]
        P.memset("gpsimd", Sb[:], 0.0, ["Sb"])

        def load(k):
            isctx, idx = RW_ORDER2[k]
            tg = 16 if isctx else idx
            b = k % 2
            for nm in ("yp", "sadd", "vst", "gst", "gyb", "gsb"):
                P.dma("sync", big[nm][b][:], S[nm][tg].rearrange("o p x -> p o x"), writes=[f"{nm}{b}"], sem=f"{nm}{b}")
            P.dma("sync", gamb[b][:].rearrange("p a b -> p (a b)"), S["gamb"][tg], writes=[f"gamb{b}"], sem=f"gamb{b}")
            P.dma("sync", bon[b][:].rearrange("p a b -> p (a b)"), S["bon"][tg], writes=[f"bon{b}"], sem=f"bon{b}")
            c0 = T if isctx else idx * 256
            P.dma("sync", xt[b][:], fm(src[:, c0:c0 + 256]), writes=[f"xt{b}"], sem=f"xt{b}")

        load(0)
        for k, (isctx, idx) in enumerate(RW_ORDER2):
            b = k % 2
            if k + 1 < len(RW_ORDER2):
                load(k + 1)
            c0 = T if isctx else idx * 256
            _, _, gates = mod_scalars(G, 0, 0, isctx)
            bc = lambda ap: ap.unsqueeze(2).broadcast_to([128, 8, 64])
            def chain_part(u):
                us = slice(u * 64, (u + 1) * 64)
                q_ = u % 2
                for oc in range(8):
                    P.mm(py[q_][:, oc * 64:(oc + 1) * 64], big["gyb"][b][:, oc, u * 128:(u + 1) * 128], Sb[:, oc, :], True, True, [f"gyb{b}", "Sb"], [f"py{q_}"])
                for oc in range(8):
                    P.mm(pS[:, oc * 64:(oc + 1) * 64], big["gsb"][b][:, oc, u * 128:(u + 1) * 128], Sb[:, oc, :], True, True, [f"gsb{b}", "Sb"], ["pS"])
                pS3 = pS[:].rearrange("p (o v) -> p o v", v=64)
                P.tt("vector", tmpS[:], pS3, big["sadd"][b][:, :, us], ALU.add, ["pS", f"sadd{b}"], ["tmpS"])
                P.tt("vector", Sb[:], tmpS[:], bc(gamb[b][:, :, u]), ALU.mult, ["tmpS", f"gamb{b}"], ["Sb"])

            def read_part(u):
                us = slice(u * 64, (u + 1) * 64)
                q_ = u % 2
                ysb, ysq, yn, bv, ob, st = ysb2[q_], ysq2[q_], yn2[q_], bv2[q_], ob2[q_], st2[q_]
                N = lambda nm: f"{nm}{q_}"
                py3 = py[q_][:].rearrange("p (o v) -> p o v", v=64)
                P.tt("vector", ysb[:], py3, big["yp"][b][:, :, us], ALU.add, [f"py{q_}", f"yp{b}"], [N("ysb")])
                P.tt("gpsimd", bv[:], big["vst"][b][:, :, us], bc(bon[b][:, :, u]), ALU.mult, [f"vst{b}", f"bon{b}"], [N("bv")])
                yield
                P.op("vector", lambda e: e.tensor_reduce(out=st["s1"][:], in_=ysb[:], axis=AX.X, op=ALU.add), [N("ysb")], [N("s1")])
                P.tt("gpsimd", ysq[:], ysb[:], ysb[:], ALU.mult, [N("ysb")], [N("ysq")])
                yield
                P.op("vector", lambda e: e.tensor_reduce(out=st["s2"][:], in_=ysq[:], axis=AX.X, op=ALU.add), [N("ysq")], [N("s2")])
                P.ts("vector", st["mean"][:], st["s1"][:], 1.0 / 64, None, ALU.mult, None, [N("s1")], [N("mean")])
                P.tt("vector", st["msq"][:], st["mean"][:], st["mean"][:], ALU.mult, [N("mean")], [N("msq")])
                P.stt(st["var"][:], st["s2"][:], 1.0 / 64, st["msq"][:], ALU.mult, ALU.subtract, [N("s2"), N("msq")], [N("var")])
                yield
                P.act(st["lnv"][:], st["var"][:], AF.Ln, [N("var")], [N("lnv")], bias=GN_EPS)
                P.act(st["rstd"][:], st["lnv"][:], AF.Exp, [N("lnv")], [N("rstd")], scale=-0.5)
                P.tt("gpsimd", yn[:], ysb[:], bc(st["mean"][:]), ALU.subtract, [N("ysb"), N("mean")], [N("yn")])
                yield
                P.tt("vector", yn[:], yn[:], bc(st["rstd"][:]), ALU.mult, [N("yn"), N("rstd")], [N("yn")])
                yield
                P.tt("gpsimd", yn[:], yn[:], lnw[:], ALU.mult, [N("yn"), "lnw"], [N("yn")])
                yield
                P.tt("vector", yn[:], yn[:], lnb[:], ALU.add, [N("yn"), "lnb"], [N("yn")])
                yield
                P.tt("gpsimd", yn[:], yn[:], bv[:], ALU.add, [N("yn"), N("bv")], [N("yn")])
                yield
                P.tt("vector", ob[:], yn[:], big["gst"][b][:, :, us], ALU.mult, [N("yn"), f"gst{b}"], [N("ob")])
                yield
                ptb = ptr[:].bitcast(BF16)
                for oc in range(8):
                    P.tr(ptb[0:64, oc * 128:(oc + 1) * 128], ob[:, oc, :], identb[:], [N("ob"), "identb"], ["ptr"])
                P.cp("scalar", OT[:, :, us], ptb[0:64, 0:1024].rearrange("p (h t) -> p h t", t=64), ["ptr"], ["OT"])
                yield

            def chain_all():
                for u in range(3, -1, -1):
                    chain_part(u)
                    yield

            jobs = [read_part(u) for u in range(3, -1, -1)]
            cgen = chain_all()
            next(cgen)
            active = []
            started = 0
            while jobs or active:
                while jobs and len(active) < 2:
                    if started >= 1:
                        try:
                            next(cgen)
                        except StopIteration:
                            pass
                    active.append(jobs.pop(0))
                    started += 1
                for gen in list(active):
                    try:
                        next(gen)
                    except StopIteration:
                        active.remove(gen)
            for oc in range(8):
                j = oc % 2
                for h in range(16):
                    P.mm(pw[j][:, 0:256], wo[:, h, oc * 128:(oc + 1) * 128], OT[:, h, :], h == 0, h == 15, ["wo", "OT"], [f"pw{j}"])
                P.stt(xt[b][:, oc, :], pw[j][:, 0:256], gates[oc], xt[b][:, oc, :], ALU.mult, ALU.add, [f"pw{j}", f"xt{b}", "modv"], [f"xt{b}"])
            P.dma("sync", fm(xa[:, c0:c0 + 256]), xt[b][:], reads=[f"xt{b}"], writes=[("xa", k)], sem=f"xt{b}")


def stage_qkv(P, io, G, hb, qtd, Kz, VA):
    vec, bones, perm = G["vec"], G["bones"], G["perm"]
    with P.phase("qkv"):
        wq = P.sb([128, 8, 1024], BF16)
        wkd = P.sb([128, 8, 512], BF16)
        wv = P.sb([128, 8, 256], BF16)
        P.dma("gpsimd", wq[:], fm(io["attn_wq"]), writes=["wq"], sem="wq")
        P.dma("gpsimd", wkd[:], fm(io["attn_wkd"]), writes=["wkd"], sem="wkd")
        P.dma("gpsimd", wv[:], fm(io["attn_wv"]), writes=["wv"], sem="wv")
        ht = [P.sb([128, 8, 512], BF16) for _ in range(2)]
        cs = [P.sb([128, 512], F32) for _ in range(2)]
        sn = [P.sb([128, 512], F32) for _ in range(2)]
        NB = 2
        qf = [P.sb([128, 512], F32) for _ in range(NB)]
        sqb = [P.sb([128, 512], BF16) for _ in range(NB)]
        lnv = [P.sb([128, 512], F32) for _ in range(NB)]
        rstd = [P.sb([128, 512], F32) for _ in range(NB)]
        qh = [P.sb([128, 512], F32) for _ in range(NB)]
        qhb = [P.sb([128, 512], BF16) for _ in range(NB)]
        t1 = [P.sb([128, 512], F32) for _ in range(NB)]
        t2 = [P.sb([128, 512], F32) for _ in range(NB)]
        qst = [P.sb([128, 8, 512], BF16) for _ in range(2)]
        pp = [P.ps([128, 512], F32) for _ in range(6)]
        cnt = [0, 0]

        def nxt():
            cnt[0] += 1
            return cnt[0] % 6

        P.memset("gpsimd", VA[:], 0.0, ["VA0"])
        P.memset("gpsimd", VA[:].rearrange("p k (j x) -> p k j x", x=65)[:, :, 0:5, 64:65], 1.0, ["VA0"])
        P.memset("gpsimd", Kz[0][64:128, :, :], 0.0, ["Kz0z"])
        P.memset("gpsimd", Kz[1][0:64, :, :], 0.0, ["Kz1z"])
        tiles = ALL_TILES

        def load(i):
            c0, tw, isctx = tiles[i]
            b = i % 2
            P.dma("sync", ht[b][:, :, :tw], fm(hb[:, c0:c0 + tw]), writes=[f"ht{b}"], sem=f"ht{b}")
            if not isctx:
                P.dma("sync", cs[b][:, :tw], io["cosT"][:, c0:c0 + tw], writes=[f"cs{b}"], sem=f"cs{b}")
                P.dma("sync", sn[b][:, :tw], io["sinT"][:, c0:c0 + tw], writes=[f"sn{b}"], sem=f"sn{b}")

        def normrope(wcols, nscal, dsts, b, tw, isctx, wname, dres="dstqk"):
            cnt[1] += 1
            n = cnt[1] % NB
            i = nxt()
            for c in range(8):
                P.mm(pp[i][:, :tw], wcols(c), ht[b][:, c, :tw], c == 0, c == 7, [wname, f"ht{b}"], [f"pp{i}"])
            P.cp("scalar", qf[n][:, :tw], pp[i][:, :tw], [f"pp{i}"], [f"qf{n}"])
            P.act(sqb[n][:, :tw], qf[n][:, :tw], AF.Square, [f"qf{n}"], [f"sqb{n}"])
            yield
            i = nxt()
            P.mm(pp[i][:, :tw], bones[:], sqb[n][:, :tw], True, True, ["bones", f"sqb{n}"], [f"pp{i}"])
            P.act(lnv[n][:, :tw], pp[i][:, :tw], AF.Ln, [f"pp{i}"], [f"lnv{n}"], bias=1e-6, scale=1.0 / 64)
            P.act(rstd[n][:, :tw], lnv[n][:, :tw], AF.Exp, [f"lnv{n}"], [f"rstd{n}"], scale=-0.5)
            yield
            P.stt(qh[n][:, :tw], qf[n][:, :tw], nscal, rstd[n][:, :tw], ALU.mult, ALU.mult, [f"qf{n}", f"rstd{n}", "vec"], [f"qh{n}"])
            if isctx:
                for dst, sl in dsts:
                    P.cp("gpsimd", dst, qh[n][sl, :tw], [f"qh{n}"], [dres])
                return
            P.cp("gpsimd", qhb[n][:, :tw], qh[n][:, :tw], [f"qh{n}"], [f"qhb{n}"])
            yield
            i = nxt()
            P.mm(pp[i][:, :tw], perm[:], qhb[n][:, :tw], True, True, ["perm", f"qhb{n}"], [f"pp{i}"])
            P.tt("gpsimd", t1[n][:, :tw], qh[n][:, :tw], cs[b][:, :tw], ALU.mult, [f"qh{n}", f"cs{b}"], [f"t1{n}"])
            P.tt("vector", t2[n][:, :tw], pp[i][:, :tw], sn[b][:, :tw], ALU.mult, [f"pp{i}", f"sn{b}"], [f"t2{n}"])
            yield
            for dst, sl in dsts:
                P.tt("gpsimd", dst, t1[n][sl, :tw], t2[n][sl, :tw], ALU.add, [f"t1{n}", f"t2{n}"], [dres])

        ALLP = slice(0, 128)
        load(0)
        for i, (c0, tw, isctx) in enumerate(tiles):
            b = i % 2
            if i + 1 < len(tiles):
                load(i + 1)
            jobs = []
            if not isctx:
                for oc in range(8):
                    jobs.append(normrope(lambda c, oc=oc: wq[:, c, oc * 128:(oc + 1) * 128], vec[:, 19, oc:oc + 1], [(qst[b][:, oc, :tw], ALLP)], b, tw, False, "wq",
                                         dres=(f"qst{b}", oc)))
            for g in range(4):
                jobs.append(normrope(lambda c, g=g: wkd[:, c, g * 128:(g + 1) * 128], vec[:, 20, 0:1],
                                     [(Kz[0][0:64, g, c0:c0 + tw], slice(0, 64)), (Kz[1][64:128, g, c0:c0 + tw], slice(64, 128))], b, tw, isctx, "wkd"))

            def vjob():
                for sub in range(tw // 128):
                    kt = c0 // 128 + sub
                    j = nxt()
                    for c in range(8):
                        P.mm(pp[j][:, 0:256], ht[b][:, c, sub * 128:(sub + 1) * 128], wv[:, c, :], c == 0, c == 7, ["wv", f"ht{b}"], [f"pp{j}"])
                    P.cp("scalar", VA[:, kt, 65:325].rearrange("p (g x) -> p g x", x=65)[:, :, 0:64],
                         pp[j][:, 0:256].rearrange("p (g d) -> p g d", d=64), [f"pp{j}", "VA0"], [("VA", kt)])
                    yield

            jobs.append(vjob())
            active = []
            while jobs or active:
                while jobs and len(active) < 2:
                    active.append(jobs.pop(0))
                for gen in list(active):
                    try:
                        next(gen)
                    except StopIteration:
                        active.remove(gen)
            if not isctx:
                P.dma("sync", fm(qtd[:, c0:c0 + tw]), qst[b][:, :, :tw], reads=[(f"qst{b}", oc) for oc in range(8)], writes=[("qtd", i)], sem=f"qst{b}")


def stage_attn(P, io, G, qtd, Kz, VA, xa):
    with P.phase("attn"):
        wo = P.sb([128, 8, 1024], BF16)
        P.dma("gpsimd", wo[:], fm(io["attn_wo"]), writes=["wo"], sem="wo")
        sel = P.sb([128, 2, 128], F32)
        P.dma("sync", sel[:], io["c_sel"], writes=["sel"], sem="sel")
        PT = [P.sb([128, 1024], BF16) for _ in range(3)]
        osb = [P.sb([128, 512], F32) for _ in range(2)]
        rb = [P.sb([128, 512], F32) for _ in range(2)]
        xt = P.sb([128, 8, 512], F32)
        QB = [P.sb([128, 8, 512], BF16) for _ in range(2)]
        psS = [P.ps([128, 1024], F32) for _ in range(2)]
        psO = [P.ps([128, 512], F32) for _ in range(2)]
        psB = P.ps([128, 512], F32)
        pX = [P.ps([128, 512], F32) for _ in range(1)]
        _, _, gates = mod_scalars(G, 1, 0, False)
        for k in range(2):
            P.memset("gpsimd", osb[k][:], 0.0, [f"osb{k}"])
        def loadq(qb):
            P.dma("sync", QB[qb % 2][:], fm(qtd[:, qb * 512:(qb + 1) * 512]), writes=[("QT", h, qb) for h in range(16)], sem=f"QB{qb % 2}")

        loadq(0)
        for qb in range(8):
            qsl = slice(qb * 512, (qb + 1) * 512)
            QT = QB[qb % 2]
            if qb + 1 < 8:
                loadq(qb + 1)
            P.dma("sync", xt[:], fm(xa[:, qsl]), writes=["xt"], sem="xt")
            steps = [(h, kp) for h in range(16) for kp in range(17)]

            def S(i):
                h, kp = steps[i]
                g, oc, h2 = h // 4, h // 2, h % 2
                for e_ in range(2):
                    kt = 2 * kp + e_
                    P.mm(psS[i % 2][:, e_ * 512:(e_ + 1) * 512], Kz[h2][:, g, kt * 128:(kt + 1) * 128], QT[:, oc, :], True, True,
                         ["Kz", ("QT", 2 * oc, qb), ("QT", 2 * oc + 1, qb)], [f"psS{i % 2}"])

            def epi_a(h):
                o = h % 2
                P.cp("vector", osb[o][:], psO[o][:], [f"psO{o}"], [f"osb{o}"])

            def epi_b(h):
                oc, h2, o = h // 2, h % 2, h % 2
                hs = slice(h2 * 64, h2 * 64 + 64)
                P.mm(psB[:, :], sel[:, h2, :], osb[o][:], True, True, ["sel", f"osb{o}"], ["psB"])
                P.act(rb[o][hs, :], psB[hs, :], AF.Ln, ["psB"], [f"rb{o}"])
                P.act(rb[o][hs, :], rb[o][hs, :], AF.Exp, [f"rb{o}"], [f"rb{o}"], scale=-1.0)
                P.tt("gpsimd", QT[hs, oc, :], osb[o][hs, :], rb[o][hs, :], ALU.mult, [f"osb{o}", f"rb{o}"], [("QT", h, qb)])

            S(0)
            pend = {}
            for i, (h, kp) in enumerate(steps):
                g, h2, o = h // 4, h % 2, h % 2
                if i + 1 < len(steps):
                    S(i + 1)
                p_ = i % 3
                P.act(PT[p_][:], psS[i % 2][:, :], AF.Exp, [f"psS{i % 2}"], [f"PT{p_}"], scale=0.125)
                v0 = 65 + 65 * g if h2 == 0 else 1 + 65 * g
                for e_ in range(2):
                    kt = 2 * kp + e_
                    P.mm(psO[o][:, :], VA[:, kt, v0:v0 + 128], PT[p_][:, e_ * 512:(e_ + 1) * 512], kt == 0, kt == 33, [f"PT{p_}", "VA"], [f"psO{o}"])
                if kp == 16:
                    epi_a(h)
                    pend[i + 3] = h
                if i in pend:
                    epi_b(pend.pop(i))
            for k in sorted(pend):
                epi_b(pend[k])
            for oc in range(8):
                j = 0
                for c in range(8):
                    P.mm(pX[j][:, :], wo[:, c, oc * 128:(oc + 1) * 128], QT[:, c, :], c == 0, c == 7,
                         ["wo", ("QT", 2 * c, qb), ("QT", 2 * c + 1, qb)], [f"pX{j}"])
                P.stt(xt[:, oc, :], pX[j][:, :], gates[oc], xt[:, oc, :], ALU.mult, ALU.add, [f"pX{j}", "xt", "modv"], ["xt"])
            P.dma("sync", fm(xa[:, qsl]), xt[:], reads=["xt"], writes=[("xa", qb)], sem="xt")


IN_SHAPES = {
    "xin": [D, TT], "cvec": [128, 8, 2], "w_mod": [2, D, 6 * D], "b_mod": [2, 6 * D], "vecs": [128, NV, 8],
    "mlp_w1": [2, D, 4 * D], "mlp_w2": [2, 4 * D, D],
    "rwkv_wr": [D, D], "rwkv_wk": [D, D], "rwkv_wv": [D, D], "rwkv_wo": [D, D],
    "rwkv_w1": [2, D, 64], "rwkv_w2": [2, 64, D], "rwkv_a1": [2, D, 64], "rwkv_a2": [2, 64, D],
    "rwkv_g1": [D, 128], "rwkv_g2": [128, D], "lnw_st": [128, 8, 64], "lnb_st": [128, 8, 64],
    "attn_wq": [D, D], "attn_wkd": [D, 512], "attn_wv": [D, 256], "attn_wo": [D, D],
    "cosT": [128, T], "sinT": [128, T],
    "c_ident": [128, 128], "c_ones": [128, 128], "c_bones": [128, 128], "c_masks": [128, 4, 128],
    "c_perm": [128, 128], "c_rmask": [128, 256], "c_sel": [128, 2, 128],
}


class IO(dict):
    def __init__(self, nc):
        super().__init__()
        self.nc = nc
        self.used = []

    def __missing__(self, k):
        ap = self.nc.dram_tensor(k, IN_SHAPES[k], F32, kind="ExternalInput").ap()
        self[k] = ap
        self.used.append(k)
        return ap

    def scratch(self, name, shape, dtype):
        return self.nc.dram_tensor(name, list(shape), dtype, kind="Internal").ap()

    def output(self, name, shape, dtype=F32):
        return self.nc.dram_tensor(name, list(shape), dtype, kind="ExternalOutput").ap()


def build(stages="all", dbg=None):
    nc = bass.Bass("TRN2", target_bir_lowering=False)
    io = IO(nc)
    P = Prog(nc)
    G = {}
    outs = {}
    stage_init(P, io, G)
    xa = io.scratch("xa", [D, TT], F32)
    hb = io.scratch("hb", [D, TT], BF16)
    if stages == "t_mlp":
        outs["dbg_h"] = io.output("dbg_h", [D, TT], BF16)
        stage_norm(P, io, G, "n_t", io["xin"], ALL_TILES,
                   lambda ic: mod_scalars(G, 0, 1, ic)[0], lambda ic: mod_scalars(G, 0, 1, ic)[1],
                   lambda c0, tw, ic: fm(hb[:, c0:c0 + tw]), BF16)
        with P.phase("copy"):
            P.dma("sync", xa, io["xin"], writes=["xa"], sem="cpa")
            P.dma("sync", outs["dbg_h"], hb, writes=["o"], sem="cpb")
        stage_mlp(P, io, G, 0, ALL_TILES, xa, hb)
        outs["y"] = io.output("y", [D, TT])
        fin = [G["vec"][:, 4, c:c + 1] for c in range(8)]
        stage_norm(P, io, G, "final", xa, ALL_TILES, lambda ic: fin, lambda ic: None,
                   lambda c0, tw, ic: fm(outs["y"][:, c0:c0 + tw]), F32)
    if stages in ("all", "l0", "l1pre"):
        hp = io.scratch("hp", [D, 4608], F32)
        S = rw_scratch(io)
        with P.phase("zpad"):
            z = P.sb([128, 8, 64], F32)
            P.memset("vector", z[:], 0.0, ["z"])
            for k, o in enumerate((0, 64 + T, 4224, 4288 + C)):
                P.dma("sync", fm(hp[:, o:o + 64]), z[:], reads=["z"], writes=[("hpz", k)], sem=f"z{k}")

        def hdst(c0, tw, ic):
            o = 4288 if ic else 64 + c0
            return fm(hp[:, o:o + tw])

        def hbdst(c0, tw, ic):
            return fm(hb[:, c0:c0 + tw])

        def ms(l, kind, which):
            return lambda ic: mod_scalars(G, l, kind, ic)[which]

        stage_norm(P, io, G, "n_mix0", io["xin"], ALL_TILES, ms(0, 0, 0), ms(0, 0, 1), hdst, F32)
        stage_rwkv1a(P, io, G, hp, S)
        stage_rwkv1b(P, io, G, S)
        stage_rwkv2(P, io, G, S, io["xin"], xa)
        stage_norm(P, io, G, "n_mlp0", xa, ALL_TILES, ms(0, 1, 0), ms(0, 1, 1), hbdst, BF16)
        stage_mlp(P, io, G, 0, ALL_TILES, xa, hb)
        if stages == "l0":
            outs["y"] = io.output("y", [D, TT])
            with P.phase("copyout"):
                P.dma("sync", outs["y"], xa, writes=["o"], sem="cpa")
        else:
            stage_norm(P, io, G, "n_mix1", xa, ALL_TILES, ms(1, 0, 0), ms(1, 0, 1), hbdst, BF16)
            with P.scope():
                QT = io.scratch("qtd", [D, T], BF16)
                Kz = [P.ssb([128, 4, TT], BF16, f"Kz{k}") for k in range(2)]
                VA = P.ssb([128, 34, 390], BF16, "VA")
                stage_qkv(P, io, G, hb, QT, Kz, VA)
                stage_attn(P, io, G, QT, Kz, VA, xa)
            if stages == "l1pre":
                outs["y"] = io.output("y", [D, TT])
                with P.phase("copyout"):
                    P.dma("sync", outs["y"], xa, writes=["o"], sem="cpa")
            else:
                stage_norm(P, io, G, "n_mlp1", xa, LAT_TILES, ms(1, 1, 0), ms(1, 1, 1), hbdst, BF16)
                stage_mlp(P, io, G, 1, LAT_TILES, xa, hb)
                outs["y"] = io.output("y", [D, T])
                fin = [G["vec"][:, 4, c:c + 1] for c in range(8)]
                stage_norm(P, io, G, "final", xa, LAT_TILES, lambda ic: fin, lambda ic: None,
                           lambda c0, tw, ic: fm(outs["y"][:, c0:c0 + tw]), F32)
    if stages == "t_rwkv":
        hp = io.scratch("hp", [D, 4608], F32)
        S = rw_scratch(io)
        with P.phase("zpad"):
            z = P.sb([128, 8, 64], F32)
            P.memset("vector", z[:], 0.0, ["z"])
            for k, o in enumerate((0, 64 + T, 4224, 4288 + C)):
                P.dma("sync", fm(hp[:, o:o + 64]), z[:], reads=["z"], writes=[("hpz", k)], sem=f"z{k}")
        def hdst(c0, tw, ic):
            o = 4288 if ic else 64 + c0
            return fm(hp[:, o:o + tw])
        stage_norm(P, io, G, "n_mix0", io["xin"], ALL_TILES,
                   lambda ic: mod_scalars(G, 0, 0, ic)[0], lambda ic: mod_scalars(G, 0, 0, ic)[1], hdst, F32)
        stage_rwkv1(P, io, G, hp, S)
        stage_rwkv2(P, io, G, S, io["xin"], xa)
        outs["y"] = io.output("y", [D, TT])
        with P.phase("copyout"):
            P.dma("sync", outs["y"], xa, writes=["o"], sem="cpa")
    P.close()
    return nc, io.used, list(outs.keys()), P


def fmv(v):
    return np.ascontiguousarray(np.asarray(v, np.float32).reshape(8, 128).T)


def host_consts():
    c = {}
    c["c_ident"] = np.eye(128, dtype=np.float32)
    c["c_ones"] = np.ones((128, 128), np.float32)
    blk = np.zeros((128, 128), np.float32)
    blk[:64, :64] = 1
    blk[64:, 64:] = 1
    c["c_bones"] = blk
    i = np.arange(64)
    us = (i[:, None] < i[None, :]).astype(np.float32)
    ui = (i[:, None] <= i[None, :]).astype(np.float32)
    m = np.zeros((128, 4, 128), np.float32)
    for k, mk in enumerate([us, ui, us.T, ui.T]):
        m[:64, k, :64] = mk
        m[64:, k, 64:] = mk
    c["c_masks"] = m
    Pm = np.zeros((128, 128), np.float32)
    for d in range(128):
        if d % 32 < 16:
            Pm[d, d + 16] = -1.0
        else:
            Pm[d, d - 16] = 1.0
    c["c_perm"] = np.ascontiguousarray(Pm.T)
    sel = np.zeros((128, 2, 128), np.float32)
    sel[64, 0, :] = 1.0
    sel[63, 1, :] = 1.0
    c["c_sel"] = sel
    rm = np.ones((128, 256), np.float32)
    rm[:, ::64] = 0
    c["c_rmask"] = rm
    t = np.arange(T)
    row = (t // 64).astype(np.float32)
    col = (t % 64).astype(np.float32)
    freqs = (np.float32(10000.0) ** (-np.arange(0, 32, 2, dtype=np.float32) / np.float32(32))).astype(np.float32)
    ang = np.zeros((64, T), np.float32)
    for d in range(64):
        pos = row if d < 32 else col
        ang[d] = pos * freqs[d % 16]
    c["cosT"] = np.ascontiguousarray(np.concatenate([np.cos(ang), np.cos(ang)], 0).astype(np.float32))
    c["sinT"] = np.ascontiguousarray(np.concatenate([np.sin(ang), np.sin(ang)], 0).astype(np.float32))
    return c


def host_inputs(inp, b):
    f = lambda k: np.asarray(inp[k], np.float32)
    d = {}
    d["xin"] = np.ascontiguousarray(np.concatenate([f("x")[b].T, f("ctx")[b].T], axis=1))
    d["cvec"] = np.ascontiguousarray(np.stack([fmv(f("c")[b]), fmv(f("c_ctx"))], axis=-1))
    return d


def host_shared(inp):
    f = lambda k: np.asarray(inp[k], np.float32)
    s = dict(host_consts())
    s["w_mod"] = f("w_mod")
    s["b_mod"] = f("b_mod")
    vl = [f("norm_mix")[0], f("norm_mix")[1], f("norm_mlp")[0], f("norm_mlp")[1], f("final_norm")]
    vl += [f("rwkv_mu")[0, j] for j in range(6)]
    vl += [f("rwkv_w0")[0, 0], f("rwkv_w0")[0, 1], f("rwkv_a0")[0, 0], f("rwkv_a0")[0, 1]]
    vl += [f("rwkv_k_k")[0], f("rwkv_k_a")[0], np.zeros(D, np.float32), f("rwkv_r_k")[0].reshape(-1)]
    vl += [np.tile(f("attn_q_norm")[0], 16), np.tile(f("attn_k_norm")[0], 16)]
    assert len(vl) == NV
    s["vecs"] = np.ascontiguousarray(np.stack([fmv(v) for v in vl], axis=1))
    s["mlp_w1"] = f("mlp_w1")
    s["mlp_w2"] = f("mlp_w2")
    for k in ("wr", "wk", "wv", "wo", "w1", "w2", "a1", "a2", "g1", "g2"):
        s["rwkv_" + k] = f("rwkv_" + k)[0]
    lw = f("rwkv_ln_w")[0].reshape(8, 2, 64)
    lb = f("rwkv_ln_b")[0].reshape(8, 2, 64)
    s["lnw_st"] = np.ascontiguousarray(np.repeat(lw.transpose(1, 0, 2), 64, axis=0))
    s["lnb_st"] = np.ascontiguousarray(np.repeat(lb.transpose(1, 0, 2), 64, axis=0))
    wqkv = f("attn_wqkv")[0]
    s["attn_wq"] = np.ascontiguousarray(wqkv[:, :1024])
    wk = wqkv[:, 1024:1280].reshape(D, 4, 64)
    s["attn_wkd"] = np.ascontiguousarray(np.concatenate([wk, wk], axis=2).reshape(D, 512))
    s["attn_wv"] = np.ascontiguousarray(wqkv[:, 1280:1536])
    s["attn_wo"] = f("attn_wo")[0]
    return s


_CACHE = {}


def kernel(**inputs):
    if "prog" not in _CACHE:
        _CACHE["prog"] = build("all")
    nc, used, outnames, _ = _CACHE["prog"]
    shared = host_shared(inputs)
    in_maps = []
    for b in range(NCORES):
        hi = host_inputs(inputs, b)
        hi.update(shared)
        in_maps.append({k: hi[k] for k in used})
    res = run_bass_kernel_spmd(nc, in_maps, core_ids=list(range(NCORES)))
    out = np.stack([np.ascontiguousarray(res.results[b]["y"].T) for b in range(NCORES)], axis=0)
    return out.astype(np.float32)
```

```python
from contextlib import ExitStack, contextmanager
import re as re_mod
import numpy as np
import concourse.bass as bass
import concourse.mybir as mybir
from concourse.bass_utils import run_bass_kernel_spmd

F32 = mybir.dt.float32
BF16 = mybir.dt.bfloat16
AF = mybir.ActivationFunctionType
ALU = mybir.AluOpType
AX = mybir.AxisListType

D = 1024
T = 4096
C = 256
TT = T + C
NCORES = 8
C0 = float(np.exp(-0.5))
NV = 21
ENGS = ("tensor", "vector", "scalar", "gpsimd", "sync")


class Prog:
    def __init__(self, nc):
        self.nc = nc
        self.ges = ExitStack()
        self.sems = {}
        self.cnt = {}
        self.dpool = {False: [], True: []}
        self.seen = {e: {} for e in ENGS}
        self.n = 0
        self.pes = None
        self.total_ops = 0

    def _alloc(self, es, fn, shape, dtype, name):
        self.n += 1
        return es.enter_context(fn(name or f"t{self.n}", list(shape), dtype))

    def gsb(self, shape, dtype, name=None):
        return self._alloc(self.ges, self.nc.sbuf_tensor, shape, dtype, name)

    def sb(self, shape, dtype, name=None):
        return self._alloc(self.pes, self.nc.sbuf_tensor, shape, dtype, name)

    @contextmanager
    def scope(self):
        self.ses = ExitStack()
        yield self
        self.ses.close()
        self.ses = None

    def ssb(self, shape, dtype, name=None):
        return self._alloc(self.ses, self.nc.sbuf_tensor, shape, dtype, name)

    def ps(self, shape, dtype, name=None):
        return self._alloc(self.pes, self.nc.psum_tensor, shape, dtype, name)

    @contextmanager
    def phase(self, name):
        self.ops = []
        self.last_w = {}
        self.readers = {}
        self.last_dma = {}
        self.pes = ExitStack()
        self.pname = name
        yield self
        self._emit()
        self.pes.close()
        self.pes = None

    _PSUM_RE = re_mod.compile(r"^(pp|pa|pb|pq|pf|ps\w*|pX|py|pS|ptr|pw)\d*$")

    ns = None
    ns_set = frozenset()

    def _deps(self, reads, writes):
        if self.ns is not None:
            reads = tuple((r, self.ns) if r in self.ns_set else r for r in reads)
            writes = tuple((w, self.ns) if w in self.ns_set else w for w in writes)
        extra = tuple(r for r in reads if isinstance(r, str) and self._PSUM_RE.match(r) and r not in writes)
        if extra:
            writes = tuple(writes) + extra
        deps = {}
        for r in reads:
            if r in self.last_w:
                deps.setdefault(self.last_w[r], set()).add("RAW")
        for w in writes:
            if w in self.last_w:
                deps.setdefault(self.last_w[w], set()).add("WAW")
            for rd in self.readers.get(w, ()):
                deps.setdefault(rd, set()).add("WAR")
        idx = len(self.ops)
        for r in reads:
            self.readers.setdefault(r, []).append(idx)
        for w in writes:
            self.last_w[w] = idx
            self.readers[w] = []
        return deps

    def op(self, eng, fn, reads=(), writes=()):
        deps = self._deps(tuple(reads), tuple(writes))
        self.ops.append(dict(eng=eng, fn=fn, deps=deps, dma=None))
        return len(self.ops) - 1

    def dma(self, queue, out, in_, reads=(), writes=(), sem=None):
        deps = self._deps(tuple(reads), tuple(writes))
        prev = self.last_dma.get(sem)
        if prev is not None:
            deps.setdefault(prev, set()).add("SER")
        idx = len(self.ops)
        self.last_dma[sem] = idx
        self.ops.append(dict(eng=queue, fn=lambda e: e.dma_start(out=out, in_=in_), deps=deps, dma=sem))
        return idx

    def _emit(self):
        nc = self.nc
        ops = self.ops
        if self.last_dma:
            ops.append(dict(eng="sync", fn=None, deps={i: {"FIN"} for i in self.last_dma.values()}, dma=None))
        self.total_ops += len(ops)

        def needs_wait(x, d, kinds):
            if d["dma"] is not None or x["dma"] is not None:
                return True
            if d["eng"] != x["eng"]:
                return True
            if x["eng"] == "tensor":
                return False
            return bool(kinds & {"RAW", "FIN"})

        signal = [False] * len(ops)
        for x in ops:
            for di, kinds in x["deps"].items():
                d = ops[di]
                if d["dma"] is None and needs_wait(x, d, kinds):
                    signal[di] = True
        dkeys = {}
        nk = {False: 0, True: 0}
        for o in ops:
            if o["dma"] is not None and o["dma"] not in dkeys:
                sw = o["eng"] == "gpsimd"
                dkeys[o["dma"]] = (sw, nk[sw])
                nk[sw] += 1
        for sw in (False, True):
            while len(self.dpool[sw]) < nk[sw]:
                h = self.ges.enter_context(nc.semaphore(f"dq{int(sw)}_{len(self.dpool[sw])}"))
                self.dpool[sw].append([h, 0])
        for e in ENGS:
            if e not in self.sems:
                self.sems[e] = self.ges.enter_context(nc.semaphore(f"e_{e}"))
        token = [None] * len(ops)
        for i, o in enumerate(ops):
            if o["dma"] is not None:
                dk = dkeys[o["dma"]]
                slot = self.dpool[dk[0]][dk[1]]
                slot[1] += 16
                token[i] = (("d", dk), slot[1])
            elif signal[i]:
                self.cnt[o["eng"]] = self.cnt.get(o["eng"], 0) + 1
                token[i] = (("e", o["eng"]), self.cnt[o["eng"]])
        per_eng = {e: [] for e in ENGS}
        for i, o in enumerate(ops):
            per_eng[o["eng"]].append(i)

        def semh(key):
            return self.dpool[key[1][0]][key[1][1]][0] if key[0] == "d" else self.sems[key[1]]

        def run(engname, eng):
            seen = self.seen[engname]
            for i in per_eng[engname]:
                o = ops[i]
                waits = {}
                for di, kinds in o["deps"].items():
                    d = ops[di]
                    if not needs_wait(o, d, kinds):
                        continue
                    key, val = token[di]
                    if waits.get(key, 0) < val:
                        waits[key] = val
                for key, val in waits.items():
                    if seen.get(key, 0) >= val:
                        continue
                    seen[key] = val
                    eng.wait_ge(semh(key), val)
                if o["fn"] is None:
                    continue
                ins = o["fn"](eng)
                if o["dma"] is not None:
                    ins.then_inc(semh(token[i][0]), 16)
                elif signal[i]:
                    ins.then_inc(self.sems[engname], 1)

        with nc.Block() as block:
            @block.sync
            def _(e):
                run("sync", e)

            @block.tensor
            def _(e):
                run("tensor", e)

            @block.vector
            def _(e):
                run("vector", e)

            @block.scalar
            def _(e):
                run("scalar", e)

            @block.gpsimd
            def _(e):
                run("gpsimd", e)

    def close(self):
        self.ges.close()

    def mm(self, out, lhsT, rhs, start, stop, r, w):
        self.op("tensor", lambda e: e.matmul(out, lhsT=lhsT, rhs=rhs, start=start, stop=stop), r, w)

    def tr(self, out, in_, ident, r, w):
        self.op("tensor", lambda e: e.transpose(out, in_, ident), r, w)

    def tt(self, eng, out, in0, in1, op, r, w):
        self.op(eng, lambda e: e.tensor_tensor(out=out, in0=in0, in1=in1, op=op), r, w)

    def ts(self, eng, out, in0, s1, s2, op0, op1, r, w):
        if op1 is None:
            self.op(eng, lambda e: e.tensor_scalar(out=out, in0=in0, scalar1=s1, scalar2=None, op0=op0), r, w)
        else:
            self.op(eng, lambda e: e.tensor_scalar(out=out, in0=in0, scalar1=s1, scalar2=s2, op0=op0, op1=op1), r, w)

    def stt(self, out, in0, scalar, in1, op0, op1, r, w):
        self.op("vector", lambda e: e.scalar_tensor_tensor(out=out, in0=in0, scalar=scalar, in1=in1, op0=op0, op1=op1), r, w)

    def act(self, out, in_, func, r, w, bias=None, scale=None):
        kw = {}
        if bias is not None:
            kw["bias"] = bias
        if scale is not None:
            kw["scale"] = scale
        self.op("scalar", lambda e: e.activation(out=out, in_=in_, func=func, **kw), r, w)

    def cp(self, eng, out, in_, r, w):
        if eng == "scalar":
            self.op(eng, lambda e: e.activation(out=out, in_=in_, func=AF.Copy), r, w)
        else:
            self.op(eng, lambda e: e.tensor_copy(out=out, in_=in_), r, w)

    def memset(self, eng, ap, val, w):
        self.op(eng, lambda e: e.memset(ap, val), (), w)


def fm(ap2d):
    return ap2d.rearrange("(c p) n -> p c n", p=128)


LAT_TILES = [(i * 512, 512, False) for i in range(8)]
ALL_TILES = LAT_TILES + [(T, 256, True)]


def stage_init(P, io, G):
    nc = P.nc
    G["identf"] = P.gsb([128, 128], F32, "identf")
    G["identb"] = P.gsb([128, 128], BF16, "identb")
    G["onesb"] = P.gsb([128, 128], BF16, "onesb")
    G["bones"] = P.gsb([128, 128], BF16, "bones")
    G["masks"] = P.gsb([128, 4, 128], BF16, "masks")
    G["perm"] = P.gsb([128, 128], BF16, "perm")
    G["rmask"] = P.gsb([128, 256], F32, "rmask")
    G["vec"] = P.gsb([128, NV, 8], F32, "vec")
    G["modv"] = P.gsb([128, 2, 6, 8, 2], F32, "modv")
    G["gg"] = P.gsb([128, 2, 2, 8, 2], F32, "gg")
    with P.phase("init"):
        P.dma("sync", G["identf"][:], io["c_ident"], writes=["identf"], sem="identf")
        P.dma("sync", G["rmask"][:], io["c_rmask"], writes=["rmask"], sem="rmask")
        P.dma("sync", G["vec"][:], io["vecs"], writes=["vec"], sem="vec")
        P.dma("gpsimd", G["identb"][:], io["c_ident"], writes=["identb"], sem="identb")
        P.dma("gpsimd", G["onesb"][:], io["c_ones"], writes=["onesb"], sem="onesb")
        P.dma("gpsimd", G["bones"][:], io["c_bones"], writes=["bones"], sem="bones")
        P.dma("gpsimd", G["masks"][:], io["c_masks"], writes=["masks"], sem="masks")
        P.dma("gpsimd", G["perm"][:], io["c_perm"], writes=["perm"], sem="perm")
        vec = G["vec"]
        P.ts("vector", vec[:, 17, :], vec[:, 16, :], -1.0, 1.0, ALU.mult, ALU.add, ["vec"], ["vec"])
        sv = P.sb([128, 8, 2], F32)
        svs = P.sb([128, 8, 2], F32)
        P.dma("sync", sv[:], io["cvec"], writes=["sv"], sem="sv")
        P.act(svs[:], sv[:], AF.Silu, ["sv"], ["svs"])
        brow = P.sb([2, 2 * 6144], F32)
        row = P.sb([2, 2 * 6144], F32)
        P.dma("sync", brow[:], io["b_mod"].rearrange("l n -> (l n)").partition_broadcast(2), writes=["brow"], sem="brow")
        wt = [P.sb([128, 8, 512], F32) for _ in range(2)]
        psr = [P.ps([128, 512], F32) for _ in range(2)]
        pst = P.ps([128, 512], F32)
        k = 0
        for l in range(2):
            for nb in range(12):
                b = k % 2
                k += 1
                P.dma("sync", wt[b][:], fm(io["w_mod"][l, :, nb * 512:(nb + 1) * 512]), writes=[f"wt{b}"], sem=f"wt{b}")
                for c in range(8):
                    P.mm(psr[b][0:2, :], svs[:, c, :], wt[b][:, c, :], c == 0, c == 7, ["svs", f"wt{b}"], [f"psr{b}"])
                o = l * 6144 + nb * 512
                P.tt("vector", row[:, o:o + 512], psr[b][0:2, :], brow[:, o:o + 512], ALU.add, [f"psr{b}", "brow"], ["row"])
        for l in range(2):
            for blk in range(48):
                o = l * 6144 + blk * 128
                P.tr(pst[:, l * 96 + blk * 2:l * 96 + blk * 2 + 2], row[0:2, o:o + 128], G["identf"][0:2, 0:2], ["row", "identf"], ["pst"])
        P.cp("vector", G["modv"][:].rearrange("p l m c j -> p (l m c j)"), pst[:, 0:192], ["pst"], ["modv"])
        modv, gg = G["modv"], G["gg"]
        for l in range(2):
            for kind in range(2):
                sc = modv[:, l, 1 + 3 * kind, :, :]
                nv = vec[:, (0 if kind == 0 else 2) + l, :].unsqueeze(2).broadcast_to([128, 8, 2])
                P.ts("vector", gg[:, l, kind, :, :], sc, 1.0, None, ALU.add, None, ["modv"], ["gg"])
                P.tt("vector", gg[:, l, kind, :, :], gg[:, l, kind, :, :], nv, ALU.mult, ["gg", "vec"], ["gg"])


def mod_scalars(G, l, kind, isctx):
    j = 1 if isctx else 0
    gains = [G["gg"][:, l, kind, c, j:j + 1] for c in range(8)]
    shifts = [G["modv"][:, l, 3 * kind, c, j:j + 1] for c in range(8)]
    gates = [G["modv"][:, l, 3 * kind + 2, c, j:j + 1] for c in range(8)]
    return gains, shifts, gates


def stage_norm(P, io, G, name, src, tiles, gains_fn, shifts_fn, dst_fn, out_dtype):
    with P.phase(name):
        xt = [P.sb([128, 8, 512], F32) for _ in range(2)]
        sq = P.sb([128, 8, 512], BF16)
        lnv = P.sb([128, 512], F32)
        rstd = P.sb([128, 512], F32)
        tmp = [P.sb([128, 512], F32) for _ in range(2)]
        ho = [P.sb([128, 8, 512], out_dtype) for _ in range(2)]
        ps = [P.ps([128, 512], F32) for _ in range(2)]

        def load(i):
            c0, tw, _ = tiles[i]
            b = i % 2
            P.dma("sync", xt[b][:, :, :tw], fm(src[:, c0:c0 + tw]), writes=[f"xt{b}"], sem=f"xt{b}")

        load(0)
        for i, (c0, tw, isctx) in enumerate(tiles):
            b = i % 2
            if i + 1 < len(tiles):
                load(i + 1)
            gains = gains_fn(isctx)
            shifts = shifts_fn(isctx)
            P.act(sq[:, :, :tw], xt[b][:, :, :tw], AF.Square, [f"xt{b}"], ["sq"])
            for c in range(8):
                P.mm(ps[b][:, :tw], G["onesb"][:], sq[:, c, :tw], c == 0, c == 7, ["sq", "onesb"], [f"ps{b}"])
            P.act(lnv[:, :tw], ps[b][:, :tw], AF.Ln, [f"ps{b}"], ["lnv"], bias=1e-6, scale=1.0 / D)
            P.act(rstd[:, :tw], lnv[:, :tw], AF.Exp, ["lnv"], ["rstd"], scale=-0.5)
            for c in range(8):
                if shifts is None:
                    P.stt(ho[b][:, c, :tw], xt[b][:, c, :tw], gains[c], rstd[:, :tw], ALU.mult, ALU.mult,
                          [f"xt{b}", "rstd", "vec", "gg"], [f"ho{b}"])
                else:
                    t = tmp[c % 2]
                    P.stt(t[:, :tw], xt[b][:, c, :tw], gains[c], rstd[:, :tw], ALU.mult, ALU.mult,
                          [f"xt{b}", "rstd", "vec", "gg"], [f"tmp{c % 2}"])
                    P.act(ho[b][:, c, :tw], t[:, :tw], AF.Identity, [f"tmp{c % 2}", "modv"], [f"ho{b}"], bias=shifts[c])
            P.dma("sync", dst_fn(c0, tw, isctx), ho[b][:, :, :tw], reads=[f"ho{b}"], writes=[("dst", i)], sem=f"ho{b}")


def stage_mlp(P, io, G, l, tiles, xa, hb):
    for half in range(2):
        with P.phase(f"mlp{l}{half}"):
            w1 = P.sb([128, 8, 2048], BF16)
            w2 = P.sb([128, 16, 1024], BF16)
            for q in range(2):
                P.dma("gpsimd", w1[:, :, q * 1024:(q + 1) * 1024],
                      fm(io["mlp_w1"][l, :, half * 2048 + q * 1024: half * 2048 + (q + 1) * 1024]), writes=["w1"], sem=f"w1{q}")
                P.dma("gpsimd", w2[:, q * 8:(q + 1) * 8, :],
                      io["mlp_w2"][l, half * 2048 + q * 1024: half * 2048 + (q + 1) * 1024, :].rearrange("(f p) n -> p f n", p=128),
                      writes=["w2"], sem=f"w2{q}")
            xt = [P.sb([128, 8, 512], F32) for _ in range(2)]
            ht = [P.sb([128, 8, 512], BF16) for _ in range(2)]
            h1 = P.sb([128, 16, 512], BF16)
            r1 = [P.sb([128, 512], F32) for _ in range(2)]
            ps = [P.ps([128, 512], F32) for _ in range(4)]

            def load(i):
                c0, tw, _ = tiles[i]
                b = i % 2
                P.dma("sync", ht[b][:, :, :tw], fm(hb[:, c0:c0 + tw]), writes=[f"ht{b}"], sem=f"ht{b}")
                P.dma("sync", xt[b][:, :, :tw], fm(xa[:, c0:c0 + tw]), reads=[("xa", i)], writes=[f"xt{b}"], sem=f"xt{b}")

            load(0)
            for i, (c0, tw, isctx) in enumerate(tiles):
                b = i % 2
                if i + 1 < len(tiles):
                    load(i + 1)
                _, _, gates = mod_scalars(G, l, 1, isctx)
                for fc in range(16):
                    pb = fc % 2
                    for c in range(8):
                        P.mm(ps[pb][:, :tw], w1[:, c, fc * 128:(fc + 1) * 128], ht[b][:, c, :tw], c == 0, c == 7,
                             ["w1", f"ht{b}"], [f"ps{pb}"])
                    P.act(r1[pb][:, :tw], ps[pb][:, :tw], AF.Relu, [f"ps{pb}"], [f"r1{pb}"])
                    P.tt("gpsimd", h1[:, fc, :tw], r1[pb][:, :tw], r1[pb][:, :tw], ALU.mult, [f"r1{pb}"], [("h1", fc)])
                for oc in range(8):
                    pb = 2 + oc % 2
                    for fc in range(16):
                        P.mm(ps[pb][:, :tw], w2[:, fc, oc * 128:(oc + 1) * 128], h1[:, fc, :tw], fc == 0, fc == 15,
                             ["w2", ("h1", fc)], [f"ps{pb}"])
                    P.stt(xt[b][:, oc, :tw], ps[pb][:, :tw], gates[oc], xt[b][:, oc, :tw], ALU.mult, ALU.add,
                          [f"ps{pb}", f"xt{b}", "modv"], [f"xt{b}"])
                P.dma("sync", fm(xa[:, c0:c0 + tw]), xt[b][:, :, :tw], reads=[f"xt{b}"], writes=[("xa", i)], sem=f"xt{b}")


RW_ORDER1 = [(True, 0)] + [(False, i) for i in range(16)]
RW_ORDER2 = [(True, 0)] + [(False, i) for i in range(15, -1, -1)]


def rw_scratch(io):
    S = {}
    S["yp"] = io.scratch("rw_yp", [17, 8, 128, 256], F32)
    S["sadd"] = io.scratch("rw_sadd", [17, 8, 128, 256], F32)
    S["vst"] = io.scratch("rw_vst", [17, 8, 128, 256], F32)
    S["gst"] = io.scratch("rw_gst", [17, 8, 128, 256], F32)
    S["gyb"] = io.scratch("rw_gyb", [17, 8, 128, 512], BF16)
    S["gsb"] = io.scratch("rw_gsb", [17, 8, 128, 512], BF16)
    S["gamb"] = io.scratch("rw_gamb", [17, 128, 32], F32)
    S["bon"] = io.scratch("rw_bon", [17, 128, 32], F32)
    S["ops"] = io.scratch("rw_ops", [17, 8, 128, 2048], BF16)
    S["vb"] = io.scratch("rw_vb", [17, 8, 128, 256], BF16)
    S["gam"] = io.scratch("rw_gam", [17, 8, 128, 8], F32)
    return S


def stage_rwkv1(P, io, G, hp, S, dbg=None):
    vec, masks, identb, identf, bones, onesb, rmask = (G[k] for k in ("vec", "masks", "identb", "identf", "bones", "onesb", "rmask"))
    with P.phase("rwkv1"):
        wr = P.sb([128, 8, 1024], BF16)
        wk = P.sb([128, 8, 1024], BF16)
        wv = P.sb([128, 8, 1024], BF16)
        for w, nm in ((wr, "rwkv_wr"), (wk, "rwkv_wk"), (wv, "rwkv_wv")):
            P.dma("gpsimd", w[:], fm(io[nm]), writes=[nm], sem=nm)
        lw1 = P.sb([128, 8, 128], BF16)
        la1 = P.sb([128, 8, 128], BF16)
        g1 = P.sb([128, 8, 128], BF16)
        for d in range(2):
            P.dma("gpsimd", lw1[:, :, d * 64:(d + 1) * 64], io["rwkv_w1"][d].rearrange("(c p) j -> p c j", p=128), writes=["lw1"], sem=f"lw1{d}")
            P.dma("gpsimd", la1[:, :, d * 64:(d + 1) * 64], io["rwkv_a1"][d].rearrange("(c p) j -> p c j", p=128), writes=["la1"], sem=f"la1{d}")
        P.dma("gpsimd", g1[:], io["rwkv_g1"].rearrange("(c p) j -> p c j", p=128), writes=["g1"], sem="g1")
        w2s = P.sb([128, 1024], BF16)
        a2s = P.sb([128, 1024], BF16)
        g2 = P.sb([128, 1024], BF16)
        P.dma("gpsimd", w2s[:], io["rwkv_w2"].rearrange("d j f -> (d j) f"), writes=["w2s"], sem="w2s")
        P.dma("gpsimd", a2s[:], io["rwkv_a2"].rearrange("d j f -> (d j) f"), writes=["a2s"], sem="a2s")
        P.dma("gpsimd", g2[:], io["rwkv_g2"], writes=["g2"], sem="g2")

        hh = P.sb([128, 8, 384], F32)
        xx = P.sb([128, 8, 256], F32)
        xr = P.sb([128, 8, 256], BF16)
        xk = P.sb([128, 8, 256], BF16)
        xv = P.sb([128, 8, 256], BF16)
        xrot = P.sb([128, 8, 256], BF16)
        lwt = P.sb([128, 256], BF16)
        lat = P.sb([128, 256], BF16)
        sg = P.sb([128, 256], BF16)
        f32t = {}
        for nm in ("r", "k", "sw0", "sw1", "ag0", "ag1", "kq", "lnv", "rs", "kkn", "fac", "kd0", "kd1", "b0", "b1",
                   "L", "Lx", "Lb", "E1", "E2", "E3", "ks"):
            f32t[nm] = P.sb([128, 256], F32, "t_" + nm)
        sqb = P.sb([128, 256], BF16)
        RK = P.sb([128, 4, 2, 64], BF16)
        VTbd = P.sb([128, 4, 128], F32)
        GTbd = P.sb([128, 4, 128], F32)
        Vf = P.sb([128, 4, 64], F32)
        Gf = P.sb([128, 4, 64], F32)
        YPs = P.sb([128, 4, 64], F32)
        SAs = P.sb([128, 4, 64], F32)
        gamb_t = P.sb([128, 8, 4], F32)
        bon_t = P.sb([128, 8, 4], F32)
        Sf = P.sb([128, 8, 64], BF16)
        ARq = [[P.sb([128, 4, 2, 128], BF16, f"AR{q}{d}") for d in range(2)] for q in range(2)]
        KTq = [[P.sb([128, 4, 128], BF16, f"KT{q}{d}") for d in range(2)] for q in range(2)]
        BTq = [[P.sb([128, 4, 128], BF16, f"BT{q}{d}") for d in range(2)] for q in range(2)]
        Vbq = [P.sb([128, 4, 64], BF16, f"Vb{q}") for q in range(3)]
        gamq = [[P.sb([128, 4], F32, f"gam{q}{d}") for d in range(2)] for q in range(3)]
        inv = []
        for d in range(2):
            st = {}
            for nm, shp in (("Atok", [128, 4, 128]), ("Btok", [128, 4, 128]), ("MQ", [128, 4, 256]), ("MWa", [128, 4, 2, 128]),
                            ("MWb", [128, 4, 2, 128]), ("MTa", [128, 4, 128]), ("MTb", [128, 4, 128])):
                st[nm] = P.sb(shp, BF16, f"i{d}_{nm}")
            inv.append(st)
        fin = []
        for q in range(2):
            row = []
            for d in range(2):
                st = {}
                for nm, shp in (("Ktok", [128, 4, 128]), ("NP", [128, 4, 256]), ("XW", [128, 4, 256]), ("NVb", [128, 4, 64]),
                                ("GY", [128, 4, 128]), ("GS", [128, 4, 128])):
                    st[nm] = P.sb(shp, BF16, f"f{q}{d}_{nm}")
                row.append(st)
            fin.append(row)
        ppt = [P.ps([128, 512], F32) for _ in range(2)]
        pp = [t_[:, 0:256] for t_ in ppt]
        pf = P.ps([128, 512], F32)
        pb = [P.ps([128, 512], F32) for _ in range(5)]
        cnt = {"pp": 0, "pb": 0}
        nmod = {"pp": 2, "pb": 5}

        def nxt(kind):
            i = cnt[kind] % nmod[kind]
            cnt[kind] += 1
            return i

        for q in range(2):
            for d in range(2):
                P.memset("gpsimd", ARq[q][d][:], 0.0, [f"AR{q}{d}"])
                P.memset("gpsimd", KTq[q][d][:], 0.0, [f"KT{q}{d}"])
                P.memset("gpsimd", BTq[q][d][:], 0.0, [f"BT{q}{d}"])
        P.memset("gpsimd", RK[:], 0.0, ["RK"])
        P.memset("gpsimd", VTbd[:], 0.0, ["VTbd"])
        P.memset("gpsimd", GTbd[:], 0.0, ["GTbd"])
        P.memset("gpsimd", Sf[:], 0.0, [("Sf", p) for p in range(8)])

        def v3(ap):
            return ap.rearrange("p (u s) -> p u s", s=64)

        def u128(ap):
            return ap.rearrange("p (u x) -> p u x", x=128)

        def load_hh(ti):
            isctx, idx = RW_ORDER1[ti]
            off = 4288 if isctx else 64 + 256 * idx
            P.dma("sync", hh[:], fm(hp[:, off - 64: off + 320]), writes=["hh"], sem="hh")

        def proj8(w_cols_fn, xb, bn, extra_r):
            i = nxt("pp")
            for c in range(8):
                P.mm(pp[i], w_cols_fn(c), xb[:, c, :], c == 0, c == 7, [(bn, c)] + extra_r, [f"pp{i}"])
            return i

        def tprep(ti):
            isctx, idx = RW_ORDER1[ti]
            hc = hh[:, :, 64:320]
            XXW = [("xx", c) for c in range(8)]
            if not isctx:
                h4 = hh[:, :, 64:320].rearrange("p c (r w) -> p c r w", w=64)
                x4 = xx[:].rearrange("p c (r w) -> p c r w", w=64)
                P.tt("vector", x4[:, 0:2, :, 1:64], h4[:, 0:2, :, 0:63], h4[:, 0:2, :, 1:64], ALU.subtract, ["hh"], XXW[0:2])
                P.ts("gpsimd", x4[:, 0:2, :, 0:1], h4[:, 0:2, :, 0:1], -1.0, 0.0, ALU.mult, ALU.add, ["hh"], [("xxe", 0)])
                P.tt("vector", x4[:, 2:4, :, 0:63], h4[:, 2:4, :, 1:64], h4[:, 2:4, :, 0:63], ALU.subtract, ["hh"], XXW[2:4])
                P.ts("gpsimd", x4[:, 2:4, :, 63:64], h4[:, 2:4, :, 63:64], -1.0, 0.0, ALU.mult, ALU.add, ["hh"], [("xxe", 1)])
                P.tt("gpsimd", xx[:, 4:6, :], hh[:, 4:6, 0:256], hh[:, 4:6, 64:320], ALU.subtract, ["hh"], XXW[4:6])
                P.tt("gpsimd", xx[:, 6:8, :], hh[:, 6:8, 128:384], hh[:, 6:8, 64:320], ALU.subtract, ["hh"], XXW[6:8])
            else:
                P.tt("vector", xx[:, 0:4, :], hh[:, 0:4, 63:319], hh[:, 0:4, 64:320], ALU.subtract, ["hh"], XXW[0:4] + [("xxe", 0)])
                P.tt("gpsimd", xx[:, 4:8, :], hh[:, 4:8, 65:321], hh[:, 4:8, 64:320], ALU.subtract, ["hh"], XXW[4:8] + [("xxe", 1)])
            yield

            def mk_xj(j, buf, bn):
                for c in range(8):
                    P.stt(buf[:, c, :], xx[:, c, :], vec[:, 5 + j, c:c + 1], hc[:, c, :], ALU.mult, ALU.add,
                          [("xx", c), ("xxe", 0), ("xxe", 1), "hh", "vec"], [(bn, c)])

            mk_xj(1, xrot, "xrot")
            yield
            i = proj8(lambda c: lw1[:, c, :], xrot, "xrot", ["lw1"])
            P.act(lwt[:], pp[i], AF.Tanh, [f"pp{i}"], ["lwt"])
            yield
            mk_xj(4, xrot, "xrot")
            yield
            i = proj8(lambda c: la1[:, c, :], xrot, "xrot", ["la1"])
            P.cp("scalar", lat[:], pp[i], [f"pp{i}"], ["lat"])
            yield
            mk_xj(5, xrot, "xrot")
            yield
            i = proj8(lambda c: g1[:, c, :], xrot, "xrot", ["g1"])
            P.act(sg[:], pp[i], AF.Sigmoid, [f"pp{i}"], ["sg"])
            yield
            mk_xj(0, xr, "xr")
            yield
            mk_xj(2, xk, "xk")
            yield
            mk_xj(3, xv, "xv")
            if ti + 1 < len(RW_ORDER1):
                load_hh(ti + 1)
            yield

        def prep(ti, oc, q, z):
            isctx, idx = RW_ORDER1[ti]
            tg = 16 if isctx else idx
            cs = slice(oc * 128, (oc + 1) * 128)
            t = f32t
            AR, KT, BT, Vb, gam = ARq[q], KTq[q], BTq[q], Vbq[z], gamq[z]
            i = proj8(lambda c: wr[:, c, cs], xr, "xr", ["rwkv_wr"])
            P.cp("scalar", t["r"][:], pp[i], [f"pp{i}"], ["r"])
            i = proj8(lambda c: wk[:, c, cs], xk, "xk", ["rwkv_wk"])
            P.cp("scalar", t["k"][:], pp[i], [f"pp{i}"], ["k"])
            i = proj8(lambda c: wv[:, c, cs], xv, "xv", ["rwkv_wv"])
            vt4 = VTbd[:].rearrange("p u (h s) -> p u h s", h=2)
            for h2 in range(2):
                sl = slice(h2 * 64, (h2 + 1) * 64)
                P.cp("scalar", vt4[sl, :, h2, :], v3(pp[i][sl, :]), [f"pp{i}"], ["VTbd"])
            i = nxt("pp")
            P.mm(pp[i], g2[:, cs], sg[:], True, True, ["g2", "sg"], [f"pp{i}"])
            gt4 = GTbd[:].rearrange("p u (h s) -> p u h s", h=2)
            for h2 in range(2):
                sl = slice(h2 * 64, (h2 + 1) * 64)
                P.cp("scalar", gt4[sl, :, h2, :], v3(pp[i][sl, :]), [f"pp{i}"], ["GTbd"])
            yield
            j = nxt("pb")
            for u in range(4):
                P.tr(pb[j][:, u * 128:(u + 1) * 128], VTbd[:, u, :], identf[:], ["VTbd", "identf"], [f"pb{j}"])
            pv = u128(pb[j][:])
            for h2 in range(2):
                sl = slice(h2 * 64, (h2 + 1) * 64)
                P.cp("scalar", Vf[sl, :, :], pv[sl, :, h2 * 64:(h2 + 1) * 64], [f"pb{j}"], ["Vf"])
            P.cp("gpsimd", Vb[:], Vf[:], ["Vf"], [f"Vb{z}"])
            P.dma("sync", S["vst"][tg, oc].rearrange("p (u s) -> p u s", s=64), Vf[:], reads=["Vf"], writes=[("vst", tg, oc)], sem="Vf")
            j = nxt("pb")
            for u in range(4):
                P.tr(pb[j][:, u * 128:(u + 1) * 128], GTbd[:, u, :], identf[:], ["GTbd", "identf"], [f"pb{j}"])
            pv = u128(pb[j][:])
            for h2 in range(2):
                sl = slice(h2 * 64, (h2 + 1) * 64)
                P.cp("scalar", Gf[sl, :, :], pv[sl, :, h2 * 64:(h2 + 1) * 64], [f"pb{j}"], ["Gf"])
            P.dma("sync", S["gst"][tg, oc].rearrange("p (u s) -> p u s", s=64), Gf[:], reads=["Gf"], writes=[("gst", tg, oc)], sem="Gf")
            yield
            for d in range(2):
                dl = slice(d * 64, (d + 1) * 64)
                i = nxt("pp")
                P.mm(pp[i], w2s[dl, cs], lwt[dl, :], True, True, ["w2s", "lwt"], [f"pp{i}"])
                P.act(t[f"sw{d}"][:], pp[i], AF.Sigmoid, [f"pp{i}", "vec"], [f"sw{d}"], bias=vec[:, 11 + d, oc:oc + 1])
                i = nxt("pp")
                P.mm(pp[i], a2s[dl, cs], lat[dl, :], True, True, ["a2s", "lat"], [f"pp{i}"])
                P.act(t[f"ag{d}"][:], pp[i], AF.Sigmoid, [f"pp{i}", "vec"], [f"ag{d}"], bias=vec[:, 13 + d, oc:oc + 1])
            yield
            P.ts("vector", t["kq"][:], t["k"][:], vec[:, 15, oc:oc + 1], None, ALU.mult, None, ["k", "vec"], ["kq"])
            P.act(sqb[:], t["kq"][:], AF.Square, ["kq"], ["sqb"])
            i = nxt("pp")
            P.mm(pp[i], bones[:], sqb[:], True, True, ["bones", "sqb"], [f"pp{i}"])
            P.act(t["lnv"][:], pp[i], AF.Ln, [f"pp{i}"], ["lnv"], bias=1e-12)
            P.act(t["rs"][:], t["lnv"][:], AF.Exp, ["lnv"], ["rs"], scale=-0.5)
            P.tt("gpsimd", t["kkn"][:], t["kq"][:], t["rs"][:], ALU.mult, ["kq", "rs"], ["kkn"])
            for d in range(2):
                sw, ag, kd, bb = t[f"sw{d}"], t[f"ag{d}"], t[f"kd{d}"], t[f"b{d}"]
                EE = "gpsimd" if d == 0 else "vector"
                P.ts(EE, t["fac"][:], ag[:], vec[:, 16, oc:oc + 1], vec[:, 17, oc:oc + 1], ALU.mult, ALU.add, [f"ag{d}", "vec"], ["fac"])
                P.tt(EE, kd[:], t["k"][:], t["fac"][:], ALU.mult, ["k", "fac"], [f"kd{d}"])
                P.tt(EE, bb[:], t["kkn"][:], ag[:], ALU.mult, ["kkn", f"ag{d}"], [f"b{d}"])
                P.op("vector", lambda e, sw=sw: e.tensor_tensor_scan(out=t["L"][:], data0=rmask[:], data1=sw[:], initial=0.0,
                                                                      op0=ALU.mult, op1=ALU.add), [f"sw{d}", "rmask"], ["L"])
                L3 = v3(t["L"][:])
                if d == 0:
                    P.tt(EE, t["Lx"][:], t["L"][:], sw[:], ALU.subtract, ["L", f"sw{d}"], ["Lx"])
                    Li, Lin = t["L"], "L"
                else:
                    P.tt(EE, v3(t["Lx"][:]), L3[:, :, 63:64].broadcast_to([128, 4, 64]), L3, ALU.subtract, ["L"], ["Lx"])
                    P.tt(EE, t["Lb"][:], t["Lx"][:], sw[:], ALU.add, ["Lx", f"sw{d}"], ["Lb"])
                    Li, Lin = t["Lb"], "Lb"
                P.act(t["E1"][:], Li[:], AF.Exp, [Lin], ["E1"], scale=-C0)
                P.act(t["E3"][:], Li[:], AF.Exp, [Lin], ["E3"], scale=C0)
                P.act(t["E2"][:], t["Lx"][:], AF.Exp, ["Lx"], ["E2"], scale=-C0)
                ar5 = AR[d][:].rearrange("p u a (h s) -> p u a h s", h=2)
                kt4 = KT[d][:].rearrange("p u (h s) -> p u h s", h=2)
                bt4 = BT[d][:].rearrange("p u (h s) -> p u h s", h=2)
                for h2 in range(2):
                    sl = slice(h2 * 64, (h2 + 1) * 64)
                    P.stt(ar5[sl, :, 0, h2, :], v3(t["kkn"][sl, :]), -1.0, v3(t["E2"][sl, :]), ALU.mult, ALU.mult, ["kkn", "E2"], [f"AR{q}{d}"])
                    P.tt(EE, ar5[sl, :, 1, h2, :], v3(t["r"][sl, :]), v3(t["E1"][sl, :]), ALU.mult, ["r", "E1"], [f"AR{q}{d}"])
                    P.tt(EE, kt4[sl, :, h2, :], v3(kd[sl, :]), v3(t["E3"][sl, :]), ALU.mult, [f"kd{d}", "E3"], [f"KT{q}{d}"])
                    P.tt(EE, bt4[sl, :, h2, :], v3(bb[sl, :]), v3(t["E3"][sl, :]), ALU.mult, [f"b{d}", "E3"], [f"BT{q}{d}"])
                E13 = v3(t["E1"][:])
                gsrc = E13[:, :, 63] if d == 0 else E13[:, :, 0]
                P.cp("vector", gam[d][:], gsrc, ["E1"], [f"gam{z}{d}"])
                if d == 1:
                    P.cp("gpsimd", gamb_t[:, oc, :], gam[1][:], [f"gam{z}1"], ["gamb_t"])
                yield
            P.tt("gpsimd", t["ks"][:], t["kd0"][:], t["kd1"][:], ALU.add, ["kd0", "kd1"], ["ks"])
            for h2 in range(2):
                sl = slice(h2 * 64, (h2 + 1) * 64)
                P.stt(RK[sl, :, h2, :], v3(t["r"][sl, :]), vec[sl, 18, oc:oc + 1], v3(t["ks"][sl, :]), ALU.mult, ALU.mult, ["r", "ks", "vec"], ["RK"])
            i = nxt("pp")
            for u in range(4):
                P.mm(pp[i][:, u:u + 1], RK[:, u, :, :].rearrange("p h s -> p (h s)"), onesb[:, 0:1], True, True, ["RK", "onesb"], [f"pp{i}"])
            P.cp("scalar", bon_t[:, oc, :], pp[i][:, 0:4], [f"pp{i}"], ["bon_t"])
            if oc == 7:
                P.dma("sync", S["gamb"][tg], gamb_t[:].rearrange("p a b -> p (a b)"), reads=["gamb_t"], writes=[("gamb", tg)], sem="gamb_t")
                P.dma("sync", S["bon"][tg], bon_t[:].rearrange("p a b -> p (a b)"), reads=["bon_t"], writes=[("bon", tg)], sem="bon_t")
            yield

        def chain(ti, oc, q, d, z):
            AR, KT, BT, Vb = ARq[q][d], KTq[q][d], BTq[q][d], Vbq[z]
            ARn, KTn, BTn, Vbn = f"AR{q}{d}", f"KT{q}{d}", f"BT{q}{d}", f"Vb{z}"
            iv, fn = inv[d], fin[q][d]
            IR = lambda nm: f"i{d}_{nm}"
            FR = lambda nm: f"f{q}{d}_{nm}"
            mS, mC = (0, 2) if d == 0 else (2, 0)
            mSI = masks[:, mS:mS + 2, :].rearrange("p a b -> p (a b)").unsqueeze(1).broadcast_to([128, 4, 256])
            mCb = masks[:, mC, :].unsqueeze(1).broadcast_to([128, 4, 128])
            idb = identb[:].unsqueeze(1).broadcast_to([128, 4, 128])
            for src, srcn, dst, dstn in ((AR[:, :, 0, :], ARn, iv["Atok"], IR("Atok")), (BT[:], BTn, iv["Btok"], IR("Btok")),
                                         (KT[:], KTn, fn["Ktok"], FR("Ktok"))):
                j = nxt("pb")
                pbt = pb[j][:].bitcast(BF16)
                for u in range(4):
                    P.tr(pbt[:, u * 128:(u + 1) * 128], src[:, u, :], identb[:], [srcn, "identb"], [f"pb{j}"])
                P.cp("scalar", dst[:].rearrange("p u x -> p (u x)"), pbt[:, 0:512], [f"pb{j}"], [dstn])
            mSb = masks[:, mS, :].unsqueeze(1).broadcast_to([128, 4, 128])
            mIb = masks[:, mS + 1, :].unsqueeze(1).broadcast_to([128, 4, 128])

            def two_bank(mm_fn):
                j0, j1 = nxt("pb"), nxt("pb")
                for u in range(4):
                    mm_fn(u, pb[j0][:, u * 128:(u + 1) * 128], f"pb{j0}", pb[j1][:, u * 128:(u + 1) * 128], f"pb{j1}")
                return j0, j1

            for lhs, lhsn, dst, dstn in ((BT, BTn, iv["MQ"], IR("MQ")), (KT, KTn, fn["NP"], FR("NP"))):
                def mm_ab(u, o0, n0, o1, n1, lhs=lhs, lhsn=lhsn):
                    P.mm(o0, lhs[:, u, :], AR[:, u, 0, :], True, True, [lhsn, ARn], [n0])
                    P.mm(o1, lhs[:, u, :], AR[:, u, 1, :], True, True, [lhsn, ARn], [n1])
                j0, j1 = two_bank(mm_ab)
                P.tt("vector", dst[:, :, 0:128], u128(pb[j0][:]), mSb, ALU.mult, [f"pb{j0}", "masks"], [dstn])
                P.tt("vector", dst[:, :, 128:256], u128(pb[j1][:]), mIb, ALU.mult, [f"pb{j1}", "masks"], [dstn])
            j = nxt("pb")
            for u in range(4):
                P.mm(pb[j][:, u * 128:(u + 1) * 128], AR[:, u, 0, :], BT[:, u, :], True, True, [ARn, BTn], [f"pb{j}"])
            cur, curn, nx, nxn = iv["MWa"], IR("MWa"), iv["MWb"], IR("MWb")
            P.tt("vector", cur[:, :, 0, :], u128(pb[j][:]), mCb, ALU.mult, [f"pb{j}", "masks"], [curn])
            yield
            j = nxt("pb")
            for u in range(4):
                P.mm(pb[j][:, u * 128:(u + 1) * 128], iv["MQ"][:, u, 0:128], cur[:, u, 0, :], True, True, [IR("MQ"), curn], [f"pb{j}"])
            P.cp("scalar", nx[:, :, 0, :], u128(pb[j][:]), [f"pb{j}"], [nxn])
            P.tt("gpsimd", nx[:, :, 1, :], cur[:, :, 0, :], idb, ALU.add, [curn, "identb"], [nxn])
            j = nxt("pb")
            for u in range(4):
                P.mm(pb[j][:, u * 128:(u + 1) * 128], cur[:, u, 0, :], iv["MQ"][:, u, 0:128], True, True, [IR("MQ"), curn], [f"pb{j}"])
            curT, curTn, nxT, nxTn = iv["MTa"], IR("MTa"), iv["MTb"], IR("MTb")
            P.cp("scalar", curT[:], u128(pb[j][:]), [f"pb{j}"], [curTn])
            cur, curn, nx, nxn = nx, nxn, cur, curn
            yield
            for lev in range(1, 5):
                def mm_lev(u, o0, n0, o1, n1, cur=cur, curn=curn, curT=curT, curTn=curTn):
                    P.mm(o0, curT[:, u, :], cur[:, u, 0, :], True, True, [curTn, curn], [n0])
                    P.mm(o1, curT[:, u, :], cur[:, u, 1, :], True, True, [curTn, curn], [n1])
                j0, j1 = two_bank(mm_lev)
                P.cp("scalar", nx[:, :, 0, :], u128(pb[j0][:]), [f"pb{j0}"], [nxn])
                P.tt("vector", nx[:, :, 1, :], u128(pb[j1][:]), cur[:, :, 1, :], ALU.add, [f"pb{j1}", curn], [nxn])
                j = nxt("pb")
                for u in range(4):
                    P.mm(pb[j][:, u * 128:(u + 1) * 128], cur[:, u, 0, :], curT[:, u, :], True, True, [curn, curTn], [f"pb{j}"])
                P.cp("scalar", nxT[:], u128(pb[j][:]), [f"pb{j}"], [nxTn])
                cur, curn, nx, nxn = nx, nxn, cur, curn
                curT, curTn, nxT, nxTn = nxT, nxTn, curT, curTn
                yield
            j = nxt("pb")
            for u in range(4):
                P.mm(pb[j][:, u * 128:(u + 1) * 128], curT[:, u, :], cur[:, u, 1, :], True, True, [curTn, curn], [f"pb{j}"])
            P.tt("vector", nx[:, :, 1, :], u128(pb[j][:]), cur[:, :, 1, :], ALU.add, [f"pb{j}", curn], [nxn])
            W6, W6n = nx, nxn
            j = nxt("pb")
            for u in range(4):
                P.mm(pb[j][:, u * 64:(u + 1) * 64], fn["NP"][:, u, 0:128], Vb[:, u, :], True, True, [FR("NP"), Vbn], [f"pb{j}"])
            P.cp("scalar", fn["NVb"][:].rearrange("p u x -> p (u x)"), pb[j][:, 0:256], [f"pb{j}"], [FR("NVb")])
            yield

            def mm_d(u, o0, n0, o1, n1):
                P.mm(o0, W6[:, u, 1, :], iv["MQ"][:, u, 128:256], True, True, [W6n, IR("MQ")], [n0])
                P.mm(o1, W6[:, u, 1, :], iv["Btok"][:, u, :], True, True, [W6n, IR("Btok")], [n1])
            j0, j1 = two_bank(mm_d)
            P.cp("scalar", fn["XW"][:, :, 0:128], u128(pb[j0][:]), [f"pb{j0}"], [FR("XW")])
            P.cp("vector", fn["XW"][:, :, 128:256], u128(pb[j1][:]), [f"pb{j1}"], [FR("XW")])
            yield

            def mm_f(u, o0, n0, o1, n1):
                P.mm(o0, iv["Atok"][:, u, :], fn["XW"][:, u, 0:128], True, True, [IR("Atok"), FR("XW")], [n0])
                P.mm(o1, iv["Atok"][:, u, :], fn["XW"][:, u, 128:256], True, True, [IR("Atok"), FR("XW")], [n1])
            j0, j1 = two_bank(mm_f)
            P.tt("vector", fn["GY"][:], u128(pb[j0][:]), AR[:, :, 1, :], ALU.add, [f"pb{j0}", ARn], [FR("GY")])
            P.tt("vector", fn["GS"][:], u128(pb[j1][:]), idb, ALU.add, [f"pb{j1}", "identb"], [FR("GS")])
            yield

        def finish(ti, oc, q, z):
            isctx, idx = RW_ORDER1[ti]
            tg = 16 if isctx else idx
            sf, sb_ = fin[q]
            F0 = lambda nm: f"f{q}0_{nm}"
            F1 = lambda nm: f"f{q}1_{nm}"
            Vb, Vbn, gam = Vbq[z], f"Vb{z}", gamq[z]
            SFR = ("Sf", oc)
            for u in range(4):
                yo = pf[:, u * 64:(u + 1) * 64]
                P.mm(yo, sf["NP"][:, u, 128:256], Vb[:, u, :], True, False, [F0("NP"), Vbn], ["pf"])
                P.mm(yo, sf["XW"][:, u, 0:128], sf["NVb"][:, u, :], False, False, [F0("XW"), F0("NVb")], ["pf"])
                P.mm(yo, sb_["NP"][:, u, 128:256], Vb[:, u, :], False, False, [F1("NP"), Vbn], ["pf"])
                P.mm(yo, sb_["XW"][:, u, 0:128], sb_["NVb"][:, u, :], False, False, [F1("XW"), F1("NVb")], ["pf"])
                P.mm(yo, sf["GY"][:, u, :], Sf[:, oc, :], False, True, [F0("GY"), SFR], ["pf"])
                so = pf[:, 256:320]
                P.mm(so, sf["Ktok"][:, u, :], Vb[:, u, :], True, False, [F0("Ktok"), Vbn], ["pf"])
                P.mm(so, sf["XW"][:, u, 128:256], sf["NVb"][:, u, :], False, False, [F0("XW"), F0("NVb")], ["pf"])
                P.mm(so, sf["GS"][:, u, :], Sf[:, oc, :], False, True, [F0("GS"), SFR], ["pf"])
                P.ts("vector", Sf[:, oc, :], so, gam[0][:, u:u + 1], None, ALU.mult, None, ["pf", f"gam{z}0"], [SFR])
                yield
            P.cp("vector", YPs[:].rearrange("p u x -> p (u x)"), pf[:, 0:256], ["pf"], ["YPs"])
            P.dma("sync", S["yp"][tg, oc], YPs[:].rearrange("p u x -> p (u x)"), reads=["YPs"], writes=[("yp", tg, oc)], sem="YPs")
            j = nxt("pb")
            for u in range(4):
                so = pb[j][:, u * 64:(u + 1) * 64]
                P.mm(so, sb_["Ktok"][:, u, :], Vb[:, u, :], True, False, [F1("Ktok"), Vbn], [f"pb{j}"])
                P.mm(so, sb_["XW"][:, u, 128:256], sb_["NVb"][:, u, :], False, True, [F1("XW"), F1("NVb")], [f"pb{j}"])
            P.cp("scalar", SAs[:].rearrange("p u x -> p (u x)"), pb[j][:, 0:256], [f"pb{j}"], ["SAs"])
            P.dma("sync", S["sadd"][tg, oc], SAs[:].rearrange("p u x -> p (u x)"), reads=["SAs"], writes=[("sadd", tg, oc)], sem="SAs")
            P.dma("sync", S["gyb"][tg, oc], sb_["GY"][:].rearrange("p u x -> p (u x)"), reads=[F1("GY")], writes=[("gyb", tg, oc)], sem=F1("GY"))
            P.dma("sync", S["gsb"][tg, oc], sb_["GS"][:].rearrange("p u x -> p (u x)"), reads=[F1("GS")], writes=[("gsb", tg, oc)], sem=F1("GS"))
            yield

        NT = len(RW_ORDER1)
        NJ = NT * 8
        done = {"prep": set(), "c0": set(), "c1": set(), "fin": set(), "tprep": set()}

        def stream_P():
            for ti in range(NT):
                yield ("tprep", ti, lambda ti=ti: (ti == 0 or ("prep", (ti - 1) * 8 + 7) in donef), lambda ti=ti: tprep(ti))
                for oc in range(8):
                    k = ti * 8 + oc
                    yield ("prep", k, lambda k=k: ((k < 2 or (("c0", k - 2) in donef and ("c1", k - 2) in donef)) and (k < 3 or ("fin", k - 3) in donef)),
                           lambda ti=ti, oc=oc, k=k: prep(ti, oc, k % 2, k % 3))

        def stream_C(d):
            for k in range(NJ):
                ti, oc = divmod(k, 8)
                yield (f"c{d}", k, lambda k=k: (("prep", k) in donef and (k < 2 or ("fin", k - 2) in donef)),
                       lambda ti=ti, oc=oc, k=k: chain(ti, oc, k % 2, d, k % 3))

        def stream_F():
            for k in range(NJ):
                ti, oc = divmod(k, 8)
                yield ("fin", k, lambda k=k: (("c0", k) in donef and ("c1", k) in donef),
                       lambda ti=ti, oc=oc, k=k: finish(ti, oc, k % 2, k % 3))

        donef = set()
        load_hh(0)
        streams = [stream_C(0), stream_C(1), stream_F(), stream_P()]
        cur = [None] * 4
        pend = [None] * 4
        alive = [True] * 4
        while any(alive):
            progressed = False
            for si in range(4):
                if not alive[si]:
                    continue
                if cur[si] is None:
                    if pend[si] is None:
                        try:
                            pend[si] = next(streams[si])
                        except StopIteration:
                            alive[si] = False
                            continue
                    kind, k, ready, mk = pend[si]
                    if not ready():
                        continue
                    cur[si] = (kind, k, mk())
                    pend[si] = None
                kind, k, gen = cur[si]
                try:
                    next(gen)
                    progressed = True
                except StopIteration:
                    donef.add((kind, k))
                    cur[si] = None
                    progressed = True
            assert progressed or not any(alive), "scheduler stuck"


def stage_rwkv1a(P, io, G, hp, S):
    vec, masks, identb, identf, bones, onesb, rmask = (G[k] for k in ("vec", "masks", "identb", "identf", "bones", "onesb", "rmask"))
    with P.phase("rwkv1a"):
        wr = P.sb([128, 8, 1024], BF16)
        wk = P.sb([128, 8, 1024], BF16)
        wv = P.sb([128, 8, 1024], BF16)
        for w, nm in ((wr, "rwkv_wr"), (wk, "rwkv_wk"), (wv, "rwkv_wv")):
            P.dma("gpsimd", w[:], fm(io[nm]), writes=[nm], sem=nm)
        lw1 = P.sb([128, 8, 128], BF16)
        la1 = P.sb([128, 8, 128], BF16)
        g1 = P.sb([128, 8, 128], BF16)
        for d in range(2):
            P.dma("gpsimd", lw1[:, :, d * 64:(d + 1) * 64], io["rwkv_w1"][d].rearrange("(c p) j -> p c j", p=128), writes=["lw1"], sem=f"lw1{d}")
            P.dma("gpsimd", la1[:, :, d * 64:(d + 1) * 64], io["rwkv_a1"][d].rearrange("(c p) j -> p c j", p=128), writes=["la1"], sem=f"la1{d}")
        P.dma("gpsimd", g1[:], io["rwkv_g1"].rearrange("(c p) j -> p c j", p=128), writes=["g1"], sem="g1")
        w2s = P.sb([128, 1024], BF16)
        a2s = P.sb([128, 1024], BF16)
        g2 = P.sb([128, 1024], BF16)
        P.dma("gpsimd", w2s[:], io["rwkv_w2"].rearrange("d j f -> (d j) f"), writes=["w2s"], sem="w2s")
        P.dma("gpsimd", a2s[:], io["rwkv_a2"].rearrange("d j f -> (d j) f"), writes=["a2s"], sem="a2s")
        P.dma("gpsimd", g2[:], io["rwkv_g2"], writes=["g2"], sem="g2")

        hh = P.sb([128, 8, 384], F32)
        xx = P.sb([128, 8, 256], F32)
        xr = P.sb([128, 8, 256], BF16)
        xk = P.sb([128, 8, 256], BF16)
        xv = P.sb([128, 8, 256], BF16)
        xrot = P.sb([128, 8, 256], BF16)
        lwt = P.sb([128, 256], BF16)
        lat = P.sb([128, 256], BF16)
        sg = P.sb([128, 256], BF16)
        NSET = 3
        bufs = []
        for w_ in range(NSET):
            B_ = {"t": {}}
            for nm in ("r", "k", "sw0", "sw1", "ag0", "ag1", "kq", "lnv", "rs", "kkn", "fac", "kd0", "kd1", "b0", "b1",
                       "L", "Lx", "Lb", "E1", "E2", "E3", "ks"):
                B_["t"][nm] = P.sb([128, 256], F32, f"t{w_}_" + nm)
            B_["sqb"] = P.sb([128, 256], BF16)
            B_["RK"] = P.sb([128, 4, 2, 64], BF16)
            B_["VTbd"] = P.sb([128, 4, 128], F32)
            B_["GTbd"] = P.sb([128, 4, 128], F32)
            B_["Vf"] = P.sb([128, 4, 64], F32)
            B_["Gf"] = P.sb([128, 4, 64], F32)
            B_["ops"] = P.sb([128, 2, 4, 256], BF16)
            B_["vb"] = P.sb([128, 4, 64], BF16)
            B_["gam"] = P.sb([128, 2, 4], F32)
            bufs.append(B_)
            P.memset("gpsimd", B_["RK"][:], 0.0, [("RK", w_)])
            P.memset("gpsimd", B_["VTbd"][:], 0.0, [("VTbd", w_)])
            P.memset("gpsimd", B_["GTbd"][:], 0.0, [("GTbd", w_)])
        P.ns_set = frozenset(["r", "k", "sw0", "sw1", "ag0", "ag1", "kq", "lnv", "rs", "kkn", "fac", "kd0", "kd1", "b0", "b1",
                              "L", "Lx", "Lb", "E1", "E2", "E3", "ks", "sqb", "RK", "VTbd", "GTbd", "Vf", "Gf", "ops_st", "vb_st", "gam_st"])
        gamb_t = P.sb([128, 8, 4], F32)
        bon_t = P.sb([128, 8, 4], F32)
        ppt = [P.ps([128, 512], F32) for _ in range(4)]
        pp = [t_[:, 0:256] for t_ in ppt]
        pb = [P.ps([128, 512], F32) for _ in range(4)]
        cnt = {"pp": 0, "pb": 0}
        nmod = {"pp": 4, "pb": 4}

        def nxt(kind):
            i = cnt[kind] % nmod[kind]
            cnt[kind] += 1
            return i

        def v3(ap):
            return ap.rearrange("p (u s) -> p u s", s=64)

        def u128(ap):
            return ap.rearrange("p (u x) -> p u x", x=128)

        def load_hh(ti):
            isctx, idx = RW_ORDER1[ti]
            off = 4288 if isctx else 64 + 256 * idx
            P.dma("sync", hh[:], fm(hp[:, off - 64: off + 320]), writes=["hh"], sem="hh")

        def proj8(w_cols_fn, xb, bn, extra_r):
            i = nxt("pp")
            for c in range(8):
                P.mm(pp[i], w_cols_fn(c), xb[:, c, :], c == 0, c == 7, [(bn, c)] + extra_r, [f"pp{i}"])
            return i

        def tprep(ti):
            isctx, idx = RW_ORDER1[ti]
            hc = hh[:, :, 64:320]
            XXW = [("xx", c) for c in range(8)]
            if not isctx:
                h4 = hh[:, :, 64:320].rearrange("p c (r w) -> p c r w", w=64)
                x4 = xx[:].rearrange("p c (r w) -> p c r w", w=64)
                P.tt("vector", x4[:, 0:2, :, 1:64], h4[:, 0:2, :, 0:63], h4[:, 0:2, :, 1:64], ALU.subtract, ["hh"], XXW[0:2])
                P.ts("gpsimd", x4[:, 0:2, :, 0:1], h4[:, 0:2, :, 0:1], -1.0, 0.0, ALU.mult, ALU.add, ["hh"], [("xxe", 0)])
                P.tt("vector", x4[:, 2:4, :, 0:63], h4[:, 2:4, :, 1:64], h4[:, 2:4, :, 0:63], ALU.subtract, ["hh"], XXW[2:4])
                P.ts("gpsimd", x4[:, 2:4, :, 63:64], h4[:, 2:4, :, 63:64], -1.0, 0.0, ALU.mult, ALU.add, ["hh"], [("xxe", 1)])
                P.tt("gpsimd", xx[:, 4:6, :], hh[:, 4:6, 0:256], hh[:, 4:6, 64:320], ALU.subtract, ["hh"], XXW[4:6])
                P.tt("gpsimd", xx[:, 6:8, :], hh[:, 6:8, 128:384], hh[:, 6:8, 64:320], ALU.subtract, ["hh"], XXW[6:8])
            else:
                P.tt("vector", xx[:, 0:4, :], hh[:, 0:4, 63:319], hh[:, 0:4, 64:320], ALU.subtract, ["hh"], XXW[0:4] + [("xxe", 0)])
                P.tt("gpsimd", xx[:, 4:8, :], hh[:, 4:8, 65:321], hh[:, 4:8, 64:320], ALU.subtract, ["hh"], XXW[4:8] + [("xxe", 1)])
            yield

            def mk_xj(j, buf, bn):
                for c in range(8):
                    P.stt(buf[:, c, :], xx[:, c, :], vec[:, 5 + j, c:c + 1], hc[:, c, :], ALU.mult, ALU.add,
                          [("xx", c), ("xxe", 0), ("xxe", 1), "hh", "vec"], [(bn, c)])

            mk_xj(1, xrot, "xrot")
            yield
            i = proj8(lambda c: lw1[:, c, :], xrot, "xrot", ["lw1"])
            P.act(lwt[:], pp[i], AF.Tanh, [f"pp{i}"], ["lwt"])
            yield
            mk_xj(4, xrot, "xrot")
            yield
            i = proj8(lambda c: la1[:, c, :], xrot, "xrot", ["la1"])
            P.cp("scalar", lat[:], pp[i], [f"pp{i}"], ["lat"])
            yield
            mk_xj(5, xrot, "xrot")
            yield
            i = proj8(lambda c: g1[:, c, :], xrot, "xrot", ["g1"])
            P.act(sg[:], pp[i], AF.Sigmoid, [f"pp{i}"], ["sg"])
            yield
            mk_xj(0, xr, "xr")
            yield
            mk_xj(2, xk, "xk")
            yield
            mk_xj(3, xv, "xv")
            if ti + 1 < len(RW_ORDER1):
                load_hh(ti + 1)
            yield

        def prep(ti, oc, w):
            isctx, idx = RW_ORDER1[ti]
            tg = 16 if isctx else idx
            cs = slice(oc * 128, (oc + 1) * 128)
            B_ = bufs[w]
            t, sqb, RK, VTbd, GTbd, Vf, Gf = B_["t"], B_["sqb"], B_["RK"], B_["VTbd"], B_["GTbd"], B_["Vf"], B_["Gf"]
            ops_st, vb_st, gam_st = B_["ops"], B_["vb"], B_["gam"]
            i = proj8(lambda c: wr[:, c, cs], xr, "xr", ["rwkv_wr"])
            P.cp("scalar", t["r"][:], pp[i], [f"pp{i}"], ["r"])
            i = proj8(lambda c: wk[:, c, cs], xk, "xk", ["rwkv_wk"])
            P.cp("scalar", t["k"][:], pp[i], [f"pp{i}"], ["k"])
            i = proj8(lambda c: wv[:, c, cs], xv, "xv", ["rwkv_wv"])
            vt4 = VTbd[:].rearrange("p u (h s) -> p u h s", h=2)
            for h2 in range(2):
                sl = slice(h2 * 64, (h2 + 1) * 64)
                P.cp("scalar", vt4[sl, :, h2, :], v3(pp[i][sl, :]), [f"pp{i}"], ["VTbd"])
            i = nxt("pp")
            P.mm(pp[i], g2[:, cs], sg[:], True, True, ["g2", "sg"], [f"pp{i}"])
            gt4 = GTbd[:].rearrange("p u (h s) -> p u h s", h=2)
            for h2 in range(2):
                sl = slice(h2 * 64, (h2 + 1) * 64)
                P.cp("scalar", gt4[sl, :, h2, :], v3(pp[i][sl, :]), [f"pp{i}"], ["GTbd"])
            yield
            j = nxt("pb")
            for u in range(4):
                P.tr(pb[j][:, u * 128:(u + 1) * 128], VTbd[:, u, :], identf[:], ["VTbd", "identf"], [f"pb{j}"])
            pv = u128(pb[j][:])
            for h2 in range(2):
                sl = slice(h2 * 64, (h2 + 1) * 64)
                P.cp("scalar", Vf[sl, :, :], pv[sl, :, h2 * 64:(h2 + 1) * 64], [f"pb{j}"], ["Vf"])
            P.cp("gpsimd", vb_st[:], Vf[:], ["Vf"], ["vb_st"])
            P.dma("sync", S["vst"][tg, oc].rearrange("p (u s) -> p u s", s=64), Vf[:], reads=["Vf"], writes=[("vst", tg, oc)], sem="Vf")
            j = nxt("pb")
            for u in range(4):
                P.tr(pb[j][:, u * 128:(u + 1) * 128], GTbd[:, u, :], identf[:], ["GTbd", "identf"], [f"pb{j}"])
            pv = u128(pb[j][:])
            for h2 in range(2):
                sl = slice(h2 * 64, (h2 + 1) * 64)
                P.cp("scalar", Gf[sl, :, :], pv[sl, :, h2 * 64:(h2 + 1) * 64], [f"pb{j}"], ["Gf"])
            P.dma("sync", S["gst"][tg, oc].rearrange("p (u s) -> p u s", s=64), Gf[:], reads=["Gf"], writes=[("gst", tg, oc)], sem="Gf")
            yield
            for d in range(2):
                dl = slice(d * 64, (d + 1) * 64)
                i = nxt("pp")
                P.mm(pp[i], w2s[dl, cs], lwt[dl, :], True, True, ["w2s", "lwt"], [f"pp{i}"])
                P.act(t[f"sw{d}"][:], pp[i], AF.Sigmoid, [f"pp{i}", "vec"], [f"sw{d}"], bias=vec[:, 11 + d, oc:oc + 1])
                i = nxt("pp")
                P.mm(pp[i], a2s[dl, cs], lat[dl, :], True, True, ["a2s", "lat"], [f"pp{i}"])
                P.act(t[f"ag{d}"][:], pp[i], AF.Sigmoid, [f"pp{i}", "vec"], [f"ag{d}"], bias=vec[:, 13 + d, oc:oc + 1])
            yield
            P.ts("vector", t["kq"][:], t["k"][:], vec[:, 15, oc:oc + 1], None, ALU.mult, None, ["k", "vec"], ["kq"])
            P.act(sqb[:], t["kq"][:], AF.Square, ["kq"], ["sqb"])
            i = nxt("pp")
            P.mm(pp[i], bones[:], sqb[:], True, True, ["bones", "sqb"], [f"pp{i}"])
            P.act(t["lnv"][:], pp[i], AF.Ln, [f"pp{i}"], ["lnv"], bias=1e-12)
            P.act(t["rs"][:], t["lnv"][:], AF.Exp, ["lnv"], ["rs"], scale=-0.5)
            P.tt("gpsimd", t["kkn"][:], t["kq"][:], t["rs"][:], ALU.mult, ["kq", "rs"], ["kkn"])
            for d in range(2):
                sw, ag, kd, bb = t[f"sw{d}"], t[f"ag{d}"], t[f"kd{d}"], t[f"b{d}"]
                EE = "gpsimd" if d == 0 else "vector"
                P.ts(EE, t["fac"][:], ag[:], vec[:, 16, oc:oc + 1], vec[:, 17, oc:oc + 1], ALU.mult, ALU.add, [f"ag{d}", "vec"], ["fac"])
                P.tt(EE, kd[:], t["k"][:], t["fac"][:], ALU.mult, ["k", "fac"], [f"kd{d}"])
                P.tt(EE, bb[:], t["kkn"][:], ag[:], ALU.mult, ["kkn", f"ag{d}"], [f"b{d}"])
                P.op("vector", lambda e, sw=sw: e.tensor_tensor_scan(out=t["L"][:], data0=rmask[:], data1=sw[:], initial=0.0,
                                                                      op0=ALU.mult, op1=ALU.add), [f"sw{d}", "rmask"], ["L"])
                L3 = v3(t["L"][:])
                if d == 0:
                    P.tt(EE, t["Lx"][:], t["L"][:], sw[:], ALU.subtract, ["L", f"sw{d}"], ["Lx"])
                    Li, Lin = t["L"], "L"
                else:
                    P.tt(EE, v3(t["Lx"][:]), L3[:, :, 63:64].broadcast_to([128, 4, 64]), L3, ALU.subtract, ["L"], ["Lx"])
                    P.tt(EE, t["Lb"][:], t["Lx"][:], sw[:], ALU.add, ["Lx", f"sw{d}"], ["Lb"])
                    Li, Lin = t["Lb"], "Lb"
                P.act(t["E1"][:], Li[:], AF.Exp, [Lin], ["E1"], scale=-C0)
                P.act(t["E3"][:], Li[:], AF.Exp, [Lin], ["E3"], scale=C0)
                P.act(t["E2"][:], t["Lx"][:], AF.Exp, ["Lx"], ["E2"], scale=-C0)
                P.stt(ops_st[:, d, 0, :], t["kkn"][:], -1.0, t["E2"][:], ALU.mult, ALU.mult, ["kkn", "E2"], ["ops_st"])
                P.tt("gpsimd", ops_st[:, d, 1, :], t["r"][:], t["E1"][:], ALU.mult, ["r", "E1"], ["ops_st"])
                P.tt(EE, ops_st[:, d, 2, :], kd[:], t["E3"][:], ALU.mult, [f"kd{d}", "E3"], ["ops_st"])
                P.tt(EE, ops_st[:, d, 3, :], bb[:], t["E3"][:], ALU.mult, [f"b{d}", "E3"], ["ops_st"])
                E13 = v3(t["E1"][:])
                gsrc = E13[:, :, 63] if d == 0 else E13[:, :, 0]
                P.cp("vector", gam_st[:, d, :], gsrc, ["E1"], ["gam_st"])
                if d == 1:
                    P.cp("gpsimd", gamb_t[:, oc, :], gam_st[:, 1, :], ["gam_st"], ["gamb_t"])
                yield
            P.tt("gpsimd", t["ks"][:], t["kd0"][:], t["kd1"][:], ALU.add, ["kd0", "kd1"], ["ks"])
            for h2 in range(2):
                sl = slice(h2 * 64, (h2 + 1) * 64)
                P.stt(RK[sl, :, h2, :], v3(t["r"][sl, :]), vec[sl, 18, oc:oc + 1], v3(t["ks"][sl, :]), ALU.mult, ALU.mult, ["r", "ks", "vec"], ["RK"])
            i = nxt("pp")
            for u in range(4):
                P.mm(pp[i][:, u:u + 1], RK[:, u, :, :].rearrange("p h s -> p (h s)"), onesb[:, 0:1], True, True, ["RK", "onesb"], [f"pp{i}"])
            P.cp("scalar", bon_t[:, oc, :], pp[i][:, 0:4], [f"pp{i}"], ["bon_t"])
            P.dma("sync", S["ops"][tg, oc], ops_st[:].rearrange("p d x n -> p (d x n)"), reads=["ops_st"], writes=[("ops", tg, oc)], sem="ops_st")
            P.dma("sync", S["vb"][tg, oc], vb_st[:].rearrange("p u s -> p (u s)"), reads=["vb_st"], writes=[("vb", tg, oc)], sem="vb_st")
            P.dma("sync", S["gam"][tg, oc], gam_st[:].rearrange("p d u -> p (d u)"), reads=["gam_st"], writes=[("gam", tg, oc)], sem="gam_st")
            yield


        NT = len(RW_ORDER1)
        load_hh(0)
        for ti in range(NT):
            isctx, idx = RW_ORDER1[ti]
            tg = 16 if isctx else idx
            for _ in tprep(ti):
                pass
            jobs = [(oc % NSET, prep(ti, oc, oc % NSET)) for oc in range(8)]
            active = []
            since = 99
            while jobs or active:
                if jobs and len(active) < NSET and (since >= 3 or not active):
                    active.append(jobs.pop(0))
                    since = 0
                since += 1
                for item in list(active):
                    P.ns = item[0]
                    try:
                        next(item[1])
                    except StopIteration:
                        active.remove(item)
                    P.ns = None
            P.dma("sync", S["gamb"][tg], gamb_t[:].rearrange("p a b -> p (a b)"), reads=["gamb_t"], writes=[("gamb", tg)], sem="gamb_t")
            P.dma("sync", S["bon"][tg], bon_t[:].rearrange("p a b -> p (a b)"), reads=["bon_t"], writes=[("bon", tg)], sem="bon_t")
        P.ns_set = frozenset()


def stage_rwkv1b(P, io, G, S):
    vec, masks, identb, identf, bones, onesb, rmask = (G[k] for k in ("vec", "masks", "identb", "identf", "bones", "onesb", "rmask"))
    with P.phase("rwkv1b"):
        YPs = P.sb([128, 4, 64], F32)
        SAs = P.sb([128, 4, 64], F32)
        Sf = P.sb([128, 8, 64], BF16)
        ARq = [[P.sb([128, 4, 2, 128], BF16, f"AR{q}{d}") for d in range(2)] for q in range(3)]
        KTq = [[P.sb([128, 4, 128], BF16, f"KT{q}{d}") for d in range(2)] for q in range(3)]
        BTq = [[P.sb([128, 4, 128], BF16, f"BT{q}{d}") for d in range(2)] for q in range(3)]
        stg = [P.sb([128, 2, 4, 256], BF16, f"stg{q}") for q in range(3)]
        Vbq = [P.sb([128, 4, 64], BF16, f"Vb{q}") for q in range(4)]
        gamq = [P.sb([128, 2, 4], F32, f"gam{q}") for q in range(4)]
        inv2 = []
        for q in range(2):
            row = []
            for d in range(2):
                st = {}
                for nm, shp in (("Atok", [128, 4, 128]), ("Btok", [128, 4, 128]), ("MQ", [128, 4, 256]), ("MWa", [128, 4, 2, 128]),
                                ("MWb", [128, 4, 2, 128]), ("MTa", [128, 4, 128]), ("MTb", [128, 4, 128])):
                    st[nm] = P.sb(shp, BF16, f"i{q}{d}_{nm}")
                row.append(st)
            inv2.append(row)
        fin = []
        for q in range(2):
            row = []
            for d in range(2):
                st = {}
                for nm, shp in (("Ktok", [128, 4, 128]), ("NP", [128, 4, 256]), ("XW", [128, 4, 256]), ("NVb", [128, 4, 64]),
                                ("GY", [128, 4, 128]), ("GS", [128, 4, 128])):
                    st[nm] = P.sb(shp, BF16, f"f{q}{d}_{nm}")
                row.append(st)
            fin.append(row)
        pf = P.ps([128, 512], F32)
        pb = [P.ps([128, 512], F32) for _ in range(7)]
        cnt = {"pb": 0}
        nmod = {"pb": 7}

        def nxt(kind):
            i = cnt[kind] % nmod[kind]
            cnt[kind] += 1
            return i

        for q in range(3):
            for d in range(2):
                P.memset("gpsimd", ARq[q][d][:], 0.0, [f"AR{q}{d}"])
                P.memset("gpsimd", KTq[q][d][:], 0.0, [f"KT{q}{d}"])
                P.memset("gpsimd", BTq[q][d][:], 0.0, [f"BT{q}{d}"])
        P.memset("gpsimd", Sf[:], 0.0, [("Sf", p) for p in range(8)])

        def v3(ap):
            return ap.rearrange("p (u s) -> p u s", s=64)

        def u128(ap):
            return ap.rearrange("p (u x) -> p u x", x=128)

        def loadjob(ti, oc, a, z):
            isctx, idx = RW_ORDER1[ti]
            tg = 16 if isctx else idx
            sg_ = stg[a]
            P.dma("sync", sg_[:].rearrange("p d x n -> p (d x n)"), S["ops"][tg, oc], writes=[f"stg{a}"], sem=f"stg{a}")
            P.dma("sync", Vbq[z][:].rearrange("p u s -> p (u s)"), S["vb"][tg, oc], writes=[f"Vb{z}"], sem=f"Vb{z}")
            P.dma("sync", gamq[z][:].rearrange("p d u -> p (d u)"), S["gam"][tg, oc], writes=[f"gam{z}"], sem=f"gam{z}")
            yield
            for d in range(2):
                ar5 = ARq[a][d][:].rearrange("p u a (h s) -> p u a h s", h=2)
                kt4 = KTq[a][d][:].rearrange("p u (h s) -> p u h s", h=2)
                bt4 = BTq[a][d][:].rearrange("p u (h s) -> p u h s", h=2)
                for h2 in range(2):
                    sl = slice(h2 * 64, (h2 + 1) * 64)
                    P.cp("gpsimd", ar5[sl, :, 0, h2, :], v3(sg_[sl, d, 0, :]), [f"stg{a}"], [f"AR{a}{d}"])
                    P.cp("gpsimd", ar5[sl, :, 1, h2, :], v3(sg_[sl, d, 1, :]), [f"stg{a}"], [f"AR{a}{d}"])
                    P.cp("gpsimd", kt4[sl, :, h2, :], v3(sg_[sl, d, 2, :]), [f"stg{a}"], [f"KT{a}{d}"])
                    P.cp("gpsimd", bt4[sl, :, h2, :], v3(sg_[sl, d, 3, :]), [f"stg{a}"], [f"BT{a}{d}"])
                    yield

        def chain(ti, oc, q, d, z, a):
            AR, KT, BT, Vb = ARq[a][d], KTq[a][d], BTq[a][d], Vbq[z]
            ARn, KTn, BTn, Vbn = f"AR{a}{d}", f"KT{a}{d}", f"BT{a}{d}", f"Vb{z}"
            iv, fn = inv2[q][d], fin[q][d]
            IR = lambda nm: f"i{q}{d}_{nm}"
            FR = lambda nm: f"f{q}{d}_{nm}"
            mS, mC = (0, 2) if d == 0 else (2, 0)
            mSI = masks[:, mS:mS + 2, :].rearrange("p a b -> p (a b)").unsqueeze(1).broadcast_to([128, 4, 256])
            mCb = masks[:, mC, :].unsqueeze(1).broadcast_to([128, 4, 128])
            idb = identb[:].unsqueeze(1).broadcast_to([128, 4, 128])
            for src, srcn, dst, dstn in ((AR[:, :, 0, :], ARn, iv["Atok"], IR("Atok")), (BT[:], BTn, iv["Btok"], IR("Btok")),
                                         (KT[:], KTn, fn["Ktok"], FR("Ktok"))):
                j = nxt("pb")
                pbt = pb[j][:].bitcast(BF16)
                for u in range(4):
                    P.tr(pbt[:, u * 128:(u + 1) * 128], src[:, u, :], identb[:], [srcn, "identb"], [f"pb{j}"])
                P.cp("scalar", dst[:].rearrange("p u x -> p (u x)"), pbt[:, 0:512], [f"pb{j}"], [dstn])
            mSb = masks[:, mS, :].unsqueeze(1).broadcast_to([128, 4, 128])
            mIb = masks[:, mS + 1, :].unsqueeze(1).broadcast_to([128, 4, 128])

            def two_bank(mm_fn):
                j0, j1 = nxt("pb"), nxt("pb")
                for u in range(4):
                    mm_fn(u, pb[j0][:, u * 128:(u + 1) * 128], f"pb{j0}", pb[j1][:, u * 128:(u + 1) * 128], f"pb{j1}")
                return j0, j1

            for lhs, lhsn, dst, dstn in ((BT, BTn, iv["MQ"], IR("MQ")), (KT, KTn, fn["NP"], FR("NP"))):
                def mm_ab(u, o0, n0, o1, n1, lhs=lhs, lhsn=lhsn):
                    P.mm(o0, lhs[:, u, :], AR[:, u, 0, :], True, True, [lhsn, ARn], [n0])
                    P.mm(o1, lhs[:, u, :], AR[:, u, 1, :], True, True, [lhsn, ARn], [n1])
                j0, j1 = two_bank(mm_ab)
                P.tt("vector", dst[:, :, 0:128], u128(pb[j0][:]), mSb, ALU.mult, [f"pb{j0}", "masks"], [dstn])
                P.tt("vector", dst[:, :, 128:256], u128(pb[j1][:]), mIb, ALU.mult, [f"pb{j1}", "masks"], [dstn])
            j = nxt("pb")
            for u in range(4):
                P.mm(pb[j][:, u * 128:(u + 1) * 128], AR[:, u, 0, :], BT[:, u, :], True, True, [ARn, BTn], [f"pb{j}"])
            cur, curn, nx, nxn = iv["MWa"], IR("MWa"), iv["MWb"], IR("MWb")
            P.tt("vector", cur[:, :, 0, :], u128(pb[j][:]), mCb, ALU.mult, [f"pb{j}", "masks"], [curn])
            yield
            j = nxt("pb")
            for u in range(4):
                P.mm(pb[j][:, u * 128:(u + 1) * 128], iv["MQ"][:, u, 0:128], cur[:, u, 0, :], True, True, [IR("MQ"), curn], [f"pb{j}"])
            P.cp("scalar", nx[:, :, 0, :], u128(pb[j][:]), [f"pb{j}"], [nxn])
            P.tt("gpsimd", nx[:, :, 1, :], cur[:, :, 0, :], idb, ALU.add, [curn, "identb"], [nxn])
            j = nxt("pb")
            for u in range(4):
                P.mm(pb[j][:, u * 128:(u + 1) * 128], cur[:, u, 0, :], iv["MQ"][:, u, 0:128], True, True, [IR("MQ"), curn], [f"pb{j}"])
            curT, curTn, nxT, nxTn = iv["MTa"], IR("MTa"), iv["MTb"], IR("MTb")
            P.cp("scalar", curT[:], u128(pb[j][:]), [f"pb{j}"], [curTn])
            cur, curn, nx, nxn = nx, nxn, cur, curn
            yield
            for lev in range(1, 5):
                def mm_lev(u, o0, n0, o1, n1, cur=cur, curn=curn, curT=curT, curTn=curTn):
                    P.mm(o0, curT[:, u, :], cur[:, u, 0, :], True, True, [curTn, curn], [n0])
                    P.mm(o1, curT[:, u, :], cur[:, u, 1, :], True, True, [curTn, curn], [n1])
                j0, j1 = two_bank(mm_lev)
                P.cp("scalar", nx[:, :, 0, :], u128(pb[j0][:]), [f"pb{j0}"], [nxn])
                P.tt("vector", nx[:, :, 1, :], u128(pb[j1][:]), cur[:, :, 1, :], ALU.add, [f"pb{j1}", curn], [nxn])
                j = nxt("pb")
                for u in range(4):
                    P.mm(pb[j][:, u * 128:(u + 1) * 128], cur[:, u, 0, :], curT[:, u, :], True, True, [curn, curTn], [f"pb{j}"])
                P.cp("scalar", nxT[:], u128(pb[j][:]), [f"pb{j}"], [nxTn])
                cur, curn, nx, nxn = nx, nxn, cur, curn
                curT, curTn, nxT, nxTn = nxT, nxTn, curT, curTn
                yield
            j = nxt("pb")
            for u in range(4):
                P.mm(pb[j][:, u * 128:(u + 1) * 128], curT[:, u, :], cur[:, u, 1, :], True, True, [curTn, curn], [f"pb{j}"])
            P.tt("vector", nx[:, :, 1, :], u128(pb[j][:]), cur[:, :, 1, :], ALU.add, [f"pb{j}", curn], [nxn])
            W6, W6n = nx, nxn
            j = nxt("pb")
            for u in range(4):
                P.mm(pb[j][:, u * 64:(u + 1) * 64], fn["NP"][:, u, 0:128], Vb[:, u, :], True, True, [FR("NP"), Vbn], [f"pb{j}"])
            P.cp("scalar", fn["NVb"][:].rearrange("p u x -> p (u x)"), pb[j][:, 0:256], [f"pb{j}"], [FR("NVb")])
            yield

            def mm_d(u, o0, n0, o1, n1):
                P.mm(o0, W6[:, u, 1, :], iv["MQ"][:, u, 128:256], True, True, [W6n, IR("MQ")], [n0])
                P.mm(o1, W6[:, u, 1, :], iv["Btok"][:, u, :], True, True, [W6n, IR("Btok")], [n1])
            j0, j1 = two_bank(mm_d)
            P.cp("scalar", fn["XW"][:, :, 0:128], u128(pb[j0][:]), [f"pb{j0}"], [FR("XW")])
            P.cp("vector", fn["XW"][:, :, 128:256], u128(pb[j1][:]), [f"pb{j1}"], [FR("XW")])
            yield

            def mm_f(u, o0, n0, o1, n1):
                P.mm(o0, iv["Atok"][:, u, :], fn["XW"][:, u, 0:128], True, True, [IR("Atok"), FR("XW")], [n0])
                P.mm(o1, iv["Atok"][:, u, :], fn["XW"][:, u, 128:256], True, True, [IR("Atok"), FR("XW")], [n1])
            j0, j1 = two_bank(mm_f)
            P.tt("vector", fn["GY"][:], u128(pb[j0][:]), AR[:, :, 1, :], ALU.add, [f"pb{j0}", ARn], [FR("GY")])
            P.tt("vector", fn["GS"][:], u128(pb[j1][:]), idb, ALU.add, [f"pb{j1}", "identb"], [FR("GS")])
            yield

        def finish(ti, oc, q, z):
            isctx, idx = RW_ORDER1[ti]
            tg = 16 if isctx else idx
            sf, sb_ = fin[q]
            F0 = lambda nm: f"f{q}0_{nm}"
            F1 = lambda nm: f"f{q}1_{nm}"
            Vb, Vbn, gamz = Vbq[z], f"Vb{z}", gamq[z]
            SFR = ("Sf", oc)
            for u in range(4):
                yo = pf[:, u * 64:(u + 1) * 64]
                P.mm(yo, sf["NP"][:, u, 128:256], Vb[:, u, :], True, False, [F0("NP"), Vbn], ["pf"])
                P.mm(yo, sf["XW"][:, u, 0:128], sf["NVb"][:, u, :], False, False, [F0("XW"), F0("NVb")], ["pf"])
                P.mm(yo, sb_["NP"][:, u, 128:256], Vb[:, u, :], False, False, [F1("NP"), Vbn], ["pf"])
                P.mm(yo, sb_["XW"][:, u, 0:128], sb_["NVb"][:, u, :], False, False, [F1("XW"), F1("NVb")], ["pf"])
                P.mm(yo, sf["GY"][:, u, :], Sf[:, oc, :], False, True, [F0("GY"), SFR], ["pf"])
                so = pf[:, 256:320]
                P.mm(so, sf["Ktok"][:, u, :], Vb[:, u, :], True, False, [F0("Ktok"), Vbn], ["pf"])
                P.mm(so, sf["XW"][:, u, 128:256], sf["NVb"][:, u, :], False, False, [F0("XW"), F0("NVb")], ["pf"])
                P.mm(so, sf["GS"][:, u, :], Sf[:, oc, :], False, True, [F0("GS"), SFR], ["pf"])
                P.ts("vector", Sf[:, oc, :], so, gamz[:, 0, u:u + 1], None, ALU.mult, None, ["pf", f"gam{z}"], [SFR])
                yield
            P.cp("vector", YPs[:].rearrange("p u x -> p (u x)"), pf[:, 0:256], ["pf"], ["YPs"])
            P.dma("sync", S["yp"][tg, oc], YPs[:].rearrange("p u x -> p (u x)"), reads=["YPs"], writes=[("yp", tg, oc)], sem="YPs")
            j = nxt("pb")
            for u in range(4):
                so = pb[j][:, u * 64:(u + 1) * 64]
                P.mm(so, sb_["Ktok"][:, u, :], Vb[:, u, :], True, False, [F1("Ktok"), Vbn], [f"pb{j}"])
                P.mm(so, sb_["XW"][:, u, 128:256], sb_["NVb"][:, u, :], False, True, [F1("XW"), F1("NVb")], [f"pb{j}"])
            P.cp("scalar", SAs[:].rearrange("p u x -> p (u x)"), pb[j][:, 0:256], [f"pb{j}"], ["SAs"])
            P.dma("sync", S["sadd"][tg, oc], SAs[:].rearrange("p u x -> p (u x)"), reads=["SAs"], writes=[("sadd", tg, oc)], sem="SAs")
            P.dma("sync", S["gyb"][tg, oc], sb_["GY"][:].rearrange("p u x -> p (u x)"), reads=[F1("GY")], writes=[("gyb", tg, oc)], sem=F1("GY"))
            P.dma("sync", S["gsb"][tg, oc], sb_["GS"][:].rearrange("p u x -> p (u x)"), reads=[F1("GS")], writes=[("gsb", tg, oc)], sem=F1("GS"))
            yield


        NT = len(RW_ORDER1)
        NJ = NT * 8
        donef = set()

        def stream_L():
            for k in range(NJ):
                ti, oc = divmod(k, 8)
                yield ("load", k, lambda k=k: ((k < 3 or (("c0", k - 3) in donef and ("c1", k - 3) in donef)) and (k < 4 or ("fin", k - 4) in donef)),
                       lambda ti=ti, oc=oc, k=k: loadjob(ti, oc, k % 3, k % 4))

        def stream_C(d, par):
            for k in range(par, NJ, 2):
                ti, oc = divmod(k, 8)
                yield (f"c{d}", k, lambda k=k: (("load", k) in donef and (k < 2 or ("fin", k - 2) in donef)),
                       lambda ti=ti, oc=oc, k=k: chain(ti, oc, k % 2, d, k % 4, k % 3))

        def stream_F():
            for k in range(NJ):
                ti, oc = divmod(k, 8)
                yield ("fin", k, lambda k=k: (("c0", k) in donef and ("c1", k) in donef),
                       lambda ti=ti, oc=oc, k=k: finish(ti, oc, k % 2, k % 4))

        streams = [stream_L(), stream_C(0, 0), stream_C(1, 0), stream_C(0, 1), stream_C(1, 1), stream_F()]
        NS_ = len(streams)
        cur = [None] * NS_
        pend = [None] * NS_
        alive = [True] * NS_
        while any(alive):
            progressed = False
            for si in range(NS_):
                if not alive[si]:
                    continue
                if cur[si] is None:
                    if pend[si] is None:
                        try:
                            pend[si] = next(streams[si])
                        except StopIteration:
                            alive[si] = False
                            continue
                    kind, k, ready, mk = pend[si]
                    if not ready():
                        continue
                    cur[si] = (kind, k, mk())
                    pend[si] = None
                kind, k, gen = cur[si]
                try:
                    next(gen)
                    progressed = True
                except StopIteration:
                    donef.add((kind, k))
                    cur[si] = None
                    progressed = True
            assert progressed or not any(alive), "scheduler stuck"


def stage_rwkv2(P, io, G, S, src, xa):
    vec, identb = G["vec"], G["identb"]
    GN_EPS = 64e-5
    with P.phase("rwkv2"):
        wo = P.sb([64, 16, 1024], BF16)
        P.dma("gpsimd", wo[:], io["rwkv_wo"].rearrange("(h v) f -> v h f", v=64), writes=["wo"], sem="wo")
        lnw = P.sb([128, 8, 64], F32)
        lnb = P.sb([128, 8, 64], F32)
        P.dma("sync", lnw[:], io["lnw_st"], writes=["lnw"], sem="lnw")
        P.dma("sync", lnb[:], io["lnb_st"], writes=["lnb"], sem="lnb")
        big = {}
        for nm in ("yp", "sadd", "vst", "gst"):
            big[nm] = [P.sb([128, 8, 256], F32, f"l_{nm}{b}") for b in range(2)]
        for nm in ("gyb", "gsb"):
            big[nm] = [P.sb([128, 8, 512], BF16, f"l_{nm}{b}") for b in range(2)]
        gamb = [P.sb([128, 8, 4], F32) for _ in range(2)]
        bon = [P.sb([128, 8, 4], F32) for _ in range(2)]
        xt = [P.sb([128, 8, 256], F32) for _ in range(2)]
        Sb = P.sb([128, 8, 64], BF16)
        ysb2 = [P.sb([128, 8, 64], F32) for _ in range(2)]
        ysq2 = [P.sb([128, 8, 64], F32) for _ in range(2)]
        tmpS = P.sb([128, 8, 64], F32)
        yn2 = [P.sb([128, 8, 64], F32) for _ in range(2)]
        bv2 = [P.sb([128, 8, 64], F32) for _ in range(2)]
        ob2 = [P.sb([128, 8, 64], BF16) for _ in range(2)]
        st2 = [{nm: P.sb([128, 8], F32, f"g{k_}_" + nm) for nm in ("s1", "s2", "mean", "msq", "var", "lnv", "rstd")} for k_ in range(2)]
        OT = P.sb([64, 16, 256], BF16)
        py = [P.ps([128, 512], F32) for _ in range(2)]
        pS = P.ps([128, 512], F32)
        ptr = P.ps([128, 1024], F32)
        pw = [P.ps([128, 512], F32) for _ in range(2)]
        P.memset("gpsimd", Sb[:], 0.0, ["Sb"])

        def load(k):
            isctx, idx = RW_ORDER2[k]
            tg = 16 if isctx else idx
            b = k % 2
            for nm in ("yp", "sadd", "vst", "gst", "gyb", "gsb"):
                P.dma("sync", big[nm][b][:], S[nm][tg].rearrange("o p x -> p o x"), writes=[f"{nm}{b}"], sem=f"{nm}{b}")
            P.dma("sync", gamb[b][:].rearrange("p a b -> p (a b)"), S["gamb"][tg], writes=[f"gamb{b}"], sem=f"gamb{b}")
            P.dma("sync", bon[b][:].rearrange("p a b -> p (a b)"), S["bon"][tg], writes=[f"bon{b}"], sem=f"bon{b}")
            c0 = T if isctx else idx * 256
            P.dma("sync", xt[b][:], fm(src[:, c0:c0 + 256]), writes=[f"xt{b}"], sem=f"xt{b}")

        load(0)
        for k, (isctx, idx) in enumerate(RW_ORDER2):
            b = k % 2
            if k + 1 < len(RW_ORDER2):
                load(k + 1)
            c0 = T if isctx else idx * 256
            _, _, gates = mod_scalars(G, 0, 0, isctx)
            bc = lambda ap: ap.unsqueeze(2).broadcast_to([128, 8, 64])
            def chain_part(u):
                us = slice(u * 64, (u + 1) * 64)
                q_ = u % 2
                for oc in range(8):
                    P.mm(py[q_][:, oc * 64:(oc + 1) * 64], big["gyb"][b][:, oc, u * 128:(u + 1) * 128], Sb[:, oc, :], True, True, [f"gyb{b}", "Sb"], [f"py{q_}"])
                for oc in range(8):
                    P.mm(pS[:, oc * 64:(oc + 1) * 64], big["gsb"][b][:, oc, u * 128:(u + 1) * 128], Sb[:, oc, :], True, True, [f"gsb{b}", "Sb"], ["pS"])
                pS3 = pS[:].rearrange("p (o v) -> p o v", v=64)
                P.tt("vector", tmpS[:], pS3, big["sadd"][b][:, :, us], ALU.add, ["pS", f"sadd{b}"], ["tmpS"])
                P.tt("vector", Sb[:], tmpS[:], bc(gamb[b][:, :, u]), ALU.mult, ["tmpS", f"gamb{b}"], ["Sb"])

            def read_part(u):
                us = slice(u * 64, (u + 1) * 64)
                q_ = u % 2
                ysb, ysq, yn, bv, ob, st = ysb2[q_], ysq2[q_], yn2[q_], bv2[q_], ob2[q_], st2[q_]
                N = lambda nm: f"{nm}{q_}"
                py3 = py[q_][:].rearrange("p (o v) -> p o v", v=64)
                P.tt("vector", ysb[:], py3, big["yp"][b][:, :, us], ALU.add, [f"py{q_}", f"yp{b}"], [N("ysb")])
                P.tt("gpsimd", bv[:], big["vst"][b][:, :, us], bc(bon[b][:, :, u]), ALU.mult, [f"vst{b}", f"bon{b}"], [N("bv")])
                yield
                P.op("vector", lambda e: e.tensor_reduce(out=st["s1"][:], in_=ysb[:], axis=AX.X, op=ALU.add), [N("ysb")], [N("s1")])
                P.tt("gpsimd", ysq[:], ysb[:], ysb[:], ALU.mult, [N("ysb")], [N("ysq")])
                yield
                P.op("vector", lambda e: e.tensor_reduce(out=st["s2"][:], in_=ysq[:], axis=AX.X, op=ALU.add), [N("ysq")], [N("s2")])
                P.ts("vector", st["mean"][:], st["s1"][:], 1.0 / 64, None, ALU.mult, None, [N("s1")], [N("mean")])
                P.tt("vector", st["msq"][:], st["mean"][:], st["mean"][:], ALU.mult, [N("mean")], [N("msq")])
                P.stt(st["var"][:], st["s2"][:], 1.0 / 64, st["msq"][:], ALU.mult, ALU.subtract, [N("s2"), N("msq")], [N("var")])
                yield
                P.act(st["lnv"][:], st["var"][:], AF.Ln, [N("var")], [N("lnv")], bias=GN_EPS)
                P.act(st["rstd"][:], st["lnv"][:], AF.Exp, [N("lnv")], [N("rstd")], scale=-0.5)
                P.tt("gpsimd", yn[:], ysb[:], bc(st["mean"][:]), ALU.subtract, [N("ysb"), N("mean")], [N("yn")])
                yield
                P.tt("vector", yn[:], yn[:], bc(st["rstd"][:]), ALU.mult, [N("yn"), N("rstd")], [N("yn")])
                yield
                P.tt("gpsimd", yn[:], yn[:], lnw[:], ALU.mult, [N("yn"), "lnw"], [N("yn")])
                yield
                P.tt("vector", yn[:], yn[:], lnb[:], ALU.add, [N("yn"), "lnb"], [N("yn")])
                yield
                P.tt("gpsimd", yn[:], yn[:], bv[:], ALU.add, [N("yn"), N("bv")], [N("yn")])
                yield
                P.tt("vector", ob[:], yn[:], big["gst"][b][:, :, us], ALU.mult, [N("yn"), f"gst{b}"], [N("ob")])
                yield
                ptb = ptr[:].bitcast(BF16)
                for oc in range(8):
                    P.tr(ptb[0:64, oc * 128:(oc + 1) * 128], ob[:, oc, :], identb[:], [N("ob"), "identb"], ["ptr"])
                P.cp("scalar", OT[:, :, us], ptb[0:64, 0:1024].rearrange("p (h t) -> p h t", t=64), ["ptr"], ["OT"])
                yield

            def chain_all():
                for u in range(3, -1, -1):
                    chain_part(u)
                    yield

            jobs = [read_part(u) for u in range(3, -1, -1)]
            cgen = chain_all()
            next(cgen)
            active = []
            started = 0
            while jobs or active:
                while jobs and len(active) < 2:
                    if started >= 1:
                        try:
                            next(cgen)
                        except StopIteration:
                            pass
                    active.append(jobs.pop(0))
                    started += 1
                for gen in list(active):
                    try:
                        next(gen)
                    except StopIteration:
                        active.remove(gen)
            for oc in range(8):
                j = oc % 2
                for h in range(16):
                    P.mm(pw[j][:, 0:256], wo[:, h, oc * 128:(oc + 1) * 128], OT[:, h, :], h == 0, h == 15, ["wo", "OT"], [f"pw{j}"])
                P.stt(xt[b][:, oc, :], pw[j][:, 0:256], gates[oc], xt[b][:, oc, :], ALU.mult, ALU.add, [f"pw{j}", f"xt{b}", "modv"], [f"xt{b}"])
            P.dma("sync", fm(xa[:, c0:c0 + 256]), xt[b][:], reads=[f"xt{b}"], writes=[("xa", k)], sem=f"xt{b}")


def stage_qkv(P, io, G, hb, qtd, Kz, VA):
    vec, bones, perm = G["vec"], G["bones"], G["perm"]
    with P.phase("qkv"):
        wq = P.sb([128, 8, 1024], BF16)
        wkd = P.sb([128, 8, 512], BF16)
        wv = P.sb([128, 8, 256], BF16)
        P.dma("gpsimd", wq[:], fm(io["attn_wq"]), writes=["wq"], sem="wq")
        P.dma("gpsimd", wkd[:], fm(io["attn_wkd"]), writes=["wkd"], sem="wkd")
        P.dma("gpsimd", wv[:], fm(io["attn_wv"]), writes=["wv"], sem="wv")
        ht = [P.sb([128, 8, 512], BF16) for _ in range(2)]
        cs = [P.sb([128, 512], F32) for _ in range(2)]
        sn = [P.sb([128, 512], F32) for _ in range(2)]
        NB = 2
        qf = [P.sb([128, 512], F32) for _ in range(NB)]
        sqb = [P.sb([128, 512], BF16) for _ in range(NB)]
        lnv = [P.sb([128, 512], F32) for _ in range(NB)]
        rstd = [P.sb([128, 512], F32) for _ in range(NB)]
        qh = [P.sb([128, 512], F32) for _ in range(NB)]
        qhb = [P.sb([128, 512], BF16) for _ in range(NB)]
        t1 = [P.sb([128, 512], F32) for _ in range(NB)]
        t2 = [P.sb([128, 512], F32) for _ in range(NB)]
        qst = [P.sb([128, 8, 512], BF16) for _ in range(2)]
        pp = [P.ps([128, 512], F32) for _ in range(6)]
        cnt = [0, 0]

        def nxt():
            cnt[0] += 1
            return cnt[0] % 6

        P.memset("gpsimd", VA[:], 0.0, ["VA0"])
        P.memset("gpsimd", VA[:].rearrange("p k (j x) -> p k j x", x=65)[:, :, 0:5, 64:65], 1.0, ["VA0"])
        P.memset("gpsimd", Kz[0][64:128, :, :], 0.0, ["Kz0z"])
        P.memset("gpsimd", Kz[1][0:64, :, :], 0.0, ["Kz1z"])
        tiles = ALL_TILES

        def load(i):
            c0, tw, isctx = tiles[i]
            b = i % 2
            P.dma("sync", ht[b][:, :, :tw], fm(hb[:, c0:c0 + tw]), writes=[f"ht{b}"], sem=f"ht{b}")
            if not isctx:
                P.dma("sync", cs[b][:, :tw], io["cosT"][:, c0:c0 + tw], writes=[f"cs{b}"], sem=f"cs{b}")
                P.dma("sync", sn[b][:, :tw], io["sinT"][:, c0:c0 + tw], writes=[f"sn{b}"], sem=f"sn{b}")

        def normrope(wcols, nscal, dsts, b, tw, isctx, wname, dres="dstqk"):
            cnt[1] += 1
            n = cnt[1] % NB
            i = nxt()
            for c in range(8):
                P.mm(pp[i][:, :tw], wcols(c), ht[b][:, c, :tw], c == 0, c == 7, [wname, f"ht{b}"], [f"pp{i}"])
            P.cp("scalar", qf[n][:, :tw], pp[i][:, :tw], [f"pp{i}"], [f"qf{n}"])
            P.act(sqb[n][:, :tw], qf[n][:, :tw], AF.Square, [f"qf{n}"], [f"sqb{n}"])
            yield
            i = nxt()
            P.mm(pp[i][:, :tw], bones[:], sqb[n][:, :tw], True, True, ["bones", f"sqb{n}"], [f"pp{i}"])
            P.act(lnv[n][:, :tw], pp[i][:, :tw], AF.Ln, [f"pp{i}"], [f"lnv{n}"], bias=1e-6, scale=1.0 / 64)
            P.act(rstd[n][:, :tw], lnv[n][:, :tw], AF.Exp, [f"lnv{n}"], [f"rstd{n}"], scale=-0.5)
            yield
            P.stt(qh[n][:, :tw], qf[n][:, :tw], nscal, rstd[n][:, :tw], ALU.mult, ALU.mult, [f"qf{n}", f"rstd{n}", "vec"], [f"qh{n}"])
            if isctx:
                for dst, sl in dsts:
                    P.cp("gpsimd", dst, qh[n][sl, :tw], [f"qh{n}"], [dres])
                return
            P.cp("gpsimd", qhb[n][:, :tw], qh[n][:, :tw], [f"qh{n}"], [f"qhb{n}"])
            yield
            i = nxt()
            P.mm(pp[i][:, :tw], perm[:], qhb[n][:, :tw], True, True, ["perm", f"qhb{n}"], [f"pp{i}"])
            P.tt("gpsimd", t1[n][:, :tw], qh[n][:, :tw], cs[b][:, :tw], ALU.mult, [f"qh{n}", f"cs{b}"], [f"t1{n}"])
            P.tt("vector", t2[n][:, :tw], pp[i][:, :tw], sn[b][:, :tw], ALU.mult, [f"pp{i}", f"sn{b}"], [f"t2{n}"])
            yield
            for dst, sl in dsts:
                P.tt("gpsimd", dst, t1[n][sl, :tw], t2[n][sl, :tw], ALU.add, [f"t1{n}", f"t2{n}"], [dres])

        ALLP = slice(0, 128)
        load(0)
        for i, (c0, tw, isctx) in enumerate(tiles):
            b = i % 2
            if i + 1 < len(tiles):
                load(i + 1)
            jobs = []
            if not isctx:
                for oc in range(8):
                    jobs.append(normrope(lambda c, oc=oc: wq[:, c, oc * 128:(oc + 1) * 128], vec[:, 19, oc:oc + 1], [(qst[b][:, oc, :tw], ALLP)], b, tw, False, "wq",
                                         dres=(f"qst{b}", oc)))
            for g in range(4):
                jobs.append(normrope(lambda c, g=g: wkd[:, c, g * 128:(g + 1) * 128], vec[:, 20, 0:1],
                                     [(Kz[0][0:64, g, c0:c0 + tw], slice(0, 64)), (Kz[1][64:128, g, c0:c0 + tw], slice(64, 128))], b, tw, isctx, "wkd"))

            def vjob():
                for sub in range(tw // 128):
                    kt = c0 // 128 + sub
                    j = nxt()
                    for c in range(8):
                        P.mm(pp[j][:, 0:256], ht[b][:, c, sub * 128:(sub + 1) * 128], wv[:, c, :], c == 0, c == 7, ["wv", f"ht{b}"], [f"pp{j}"])
                    P.cp("scalar", VA[:, kt, 65:325].rearrange("p (g x) -> p g x", x=65)[:, :, 0:64],
                         pp[j][:, 0:256].rearrange("p (g d) -> p g d", d=64), [f"pp{j}", "VA0"], [("VA", kt)])
                    yield

            jobs.append(vjob())
            active = []
            while jobs or active:
                while jobs and len(active) < 2:
                    active.append(jobs.pop(0))
                for gen in list(active):
                    try:
                        next(gen)
                    except StopIteration:
                        active.remove(gen)
            if not isctx:
                P.dma("sync", fm(qtd[:, c0:c0 + tw]), qst[b][:, :, :tw], reads=[(f"qst{b}", oc) for oc in range(8)], writes=[("qtd", i)], sem=f"qst{b}")


def stage_attn(P, io, G, qtd, Kz, VA, xa):
    with P.phase("attn"):
        wo = P.sb([128, 8, 1024], BF16)
        P.dma("gpsimd", wo[:], fm(io["attn_wo"]), writes=["wo"], sem="wo")
        sel = P.sb([128, 2, 128], F32)
        P.dma("sync", sel[:], io["c_sel"], writes=["sel"], sem="sel")
        PT = [P.sb([128, 1024], BF16) for _ in range(3)]
        osb = [P.sb([128, 512], F32) for _ in range(2)]
        rb = [P.sb([128, 512], F32) for _ in range(2)]
        xt = P.sb([128, 8, 512], F32)
        QB = [P.sb([128, 8, 512], BF16) for _ in range(2)]
        psS = [P.ps([128, 1024], F32) for _ in range(2)]
        psO = [P.ps([128, 512], F32) for _ in range(2)]
        psB = P.ps([128, 512], F32)
        pX = [P.ps([128, 512], F32) for _ in range(1)]
        _, _, gates = mod_scalars(G, 1, 0, False)
        for k in range(2):
            P.memset("gpsimd", osb[k][:], 0.0, [f"osb{k}"])
        def loadq(qb):
            P.dma("sync", QB[qb % 2][:], fm(qtd[:, qb * 512:(qb + 1) * 512]), writes=[("QT", h, qb) for h in range(16)], sem=f"QB{qb % 2}")

        loadq(0)
        for qb in range(8):
            qsl = slice(qb * 512, (qb + 1) * 512)
            QT = QB[qb % 2]
            if qb + 1 < 8:
                loadq(qb + 1)
            P.dma("sync", xt[:], fm(xa[:, qsl]), writes=["xt"], sem="xt")
            steps = [(h, kp) for h in range(16) for kp in range(17)]

            def S(i):
                h, kp = steps[i]
                g, oc, h2 = h // 4, h // 2, h % 2
                for e_ in range(2):
                    kt = 2 * kp + e_
                    P.mm(psS[i % 2][:, e_ * 512:(e_ + 1) * 512], Kz[h2][:, g, kt * 128:(kt + 1) * 128], QT[:, oc, :], True, True,
                         ["Kz", ("QT", 2 * oc, qb), ("QT", 2 * oc + 1, qb)], [f"psS{i % 2}"])

            def epi_a(h):
                o = h % 2
                P.cp("vector", osb[o][:], psO[o][:], [f"psO{o}"], [f"osb{o}"])

            def epi_b(h):
                oc, h2, o = h // 2, h % 2, h % 2
                hs = slice(h2 * 64, h2 * 64 + 64)
                P.mm(psB[:, :], sel[:, h2, :], osb[o][:], True, True, ["sel", f"osb{o}"], ["psB"])
                P.act(rb[o][hs, :], psB[hs, :], AF.Ln, ["psB"], [f"rb{o}"])
                P.act(rb[o][hs, :], rb[o][hs, :], AF.Exp, [f"rb{o}"], [f"rb{o}"], scale=-1.0)
                P.tt("gpsimd", QT[hs, oc, :], osb[o][hs, :], rb[o][hs, :], ALU.mult, [f"osb{o}", f"rb{o}"], [("QT", h, qb)])

            S(0)
            pend = {}
            for i, (h, kp) in enumerate(steps):
                g, h2, o = h // 4, h % 2, h % 2
                if i + 1 < len(steps):
                    S(i + 1)
                p_ = i % 3
                P.act(PT[p_][:], psS[i % 2][:, :], AF.Exp, [f"psS{i % 2}"], [f"PT{p_}"], scale=0.125)
                v0 = 65 + 65 * g if h2 == 0 else 1 + 65 * g
                for e_ in range(2):
                    kt = 2 * kp + e_
                    P.mm(psO[o][:, :], VA[:, kt, v0:v0 + 128], PT[p_][:, e_ * 512:(e_ + 1) * 512], kt == 0, kt == 33, [f"PT{p_}", "VA"], [f"psO{o}"])
                if kp == 16:
                    epi_a(h)
                    pend[i + 3] = h
                if i in pend:
                    epi_b(pend.pop(i))
            for k in sorted(pend):
                epi_b(pend[k])
            for oc in range(8):
                j = 0
                for c in range(8):
                    P.mm(pX[j][:, :], wo[:, c, oc * 128:(oc + 1) * 128], QT[:, c, :], c == 0, c == 7,
                         ["wo", ("QT", 2 * c, qb), ("QT", 2 * c + 1, qb)], [f"pX{j}"])
                P.stt(xt[:, oc, :], pX[j][:, :], gates[oc], xt[:, oc, :], ALU.mult, ALU.add, [f"pX{j}", "xt", "modv"], ["xt"])
            P.dma("sync", fm(xa[:, qsl]), xt[:], reads=["xt"], writes=[("xa", qb)], sem="xt")


IN_SHAPES = {
    "xin": [D, TT], "cvec": [128, 8, 2], "w_mod": [2, D, 6 * D], "b_mod": [2, 6 * D], "vecs": [128, NV, 8],
    "mlp_w1": [2, D, 4 * D], "mlp_w2": [2, 4 * D, D],
    "rwkv_wr": [D, D], "rwkv_wk": [D, D], "rwkv_wv": [D, D], "rwkv_wo": [D, D],
    "rwkv_w1": [2, D, 64], "rwkv_w2": [2, 64, D], "rwkv_a1": [2, D, 64], "rwkv_a2": [2, 64, D],
    "rwkv_g1": [D, 128], "rwkv_g2": [128, D], "lnw_st": [128, 8, 64], "lnb_st": [128, 8, 64],
    "attn_wq": [D, D], "attn_wkd": [D, 512], "attn_wv": [D, 256], "attn_wo": [D, D],
    "cosT": [128, T], "sinT": [128, T],
    "c_ident": [128, 128], "c_ones": [128, 128], "c_bones": [128, 128], "c_masks": [128, 4, 128],
    "c_perm": [128, 128], "c_rmask": [128, 256], "c_sel": [128, 2, 128],
}


class IO(dict):
    def __init__(self, nc):
        super().__init__()
        self.nc = nc
        self.used = []

    def __missing__(self, k):
        ap = self.nc.dram_tensor(k, IN_SHAPES[k], F32, kind="ExternalInput").ap()
        self[k] = ap
        self.used.append(k)
        return ap

    def scratch(self, name, shape, dtype):
        return self.nc.dram_tensor(name, list(shape), dtype, kind="Internal").ap()

    def output(self, name, shape, dtype=F32):
        return self.nc.dram_tensor(name, list(shape), dtype, kind="ExternalOutput").ap()


def build(stages="all", dbg=None):
    nc = bass.Bass("TRN2", target_bir_lowering=False)
    io = IO(nc)
    P = Prog(nc)
    G = {}
    outs = {}
    stage_init(P, io, G)
    xa = io.scratch("xa", [D, TT], F32)
    hb = io.scratch("hb", [D, TT], BF16)
    if stages == "t_mlp":
        outs["dbg_h"] = io.output("dbg_h", [D, TT], BF16)
        stage_norm(P, io, G, "n_t", io["xin"], ALL_TILES,
                   lambda ic: mod_scalars(G, 0, 1, ic)[0], lambda ic: mod_scalars(G, 0, 1, ic)[1],
                   lambda c0, tw, ic: fm(hb[:, c0:c0 + tw]), BF16)
        with P.phase("copy"):
            P.dma("sync", xa, io["xin"], writes=["xa"], sem="cpa")
            P.dma("sync", outs["dbg_h"], hb, writes=["o"], sem="cpb")
        stage_mlp(P, io, G, 0, ALL_TILES, xa, hb)
        outs["y"] = io.output("y", [D, TT])
        fin = [G["vec"][:, 4, c:c + 1] for c in range(8)]
        stage_norm(P, io, G, "final", xa, ALL_TILES, lambda ic: fin, lambda ic: None,
                   lambda c0, tw, ic: fm(outs["y"][:, c0:c0 + tw]), F32)
    if stages in ("all", "l0", "l1pre"):
        hp = io.scratch("hp", [D, 4608], F32)
        S = rw_scratch(io)
        with P.phase("zpad"):
            z = P.sb([128, 8, 64], F32)
            P.memset("vector", z[:], 0.0, ["z"])
            for k, o in enumerate((0, 64 + T, 4224, 4288 + C)):
                P.dma("sync", fm(hp[:, o:o + 64]), z[:], reads=["z"], writes=[("hpz", k)], sem=f"z{k}")

        def hdst(c0, tw, ic):
            o = 4288 if ic else 64 + c0
            return fm(hp[:, o:o + tw])

        def hbdst(c0, tw, ic):
            return fm(hb[:, c0:c0 + tw])

        def ms(l, kind, which):
            return lambda ic: mod_scalars(G, l, kind, ic)[which]

        stage_norm(P, io, G, "n_mix0", io["xin"], ALL_TILES, ms(0, 0, 0), ms(0, 0, 1), hdst, F32)
        stage_rwkv1a(P, io, G, hp, S)
        stage_rwkv1b(P, io, G, S)
        stage_rwkv2(P, io, G, S, io["xin"], xa)
        stage_norm(P, io, G, "n_mlp0", xa, ALL_TILES, ms(0, 1, 0), ms(0, 1, 1), hbdst, BF16)
        stage_mlp(P, io, G, 0, ALL_TILES, xa, hb)
        if stages == "l0":
            outs["y"] = io.output("y", [D, TT])
            with P.phase("copyout"):
                P.dma("sync", outs["y"], xa, writes=["o"], sem="cpa")
        else:
            stage_norm(P, io, G, "n_mix1", xa, ALL_TILES, ms(1, 0, 0), ms(1, 0, 1), hbdst, BF16)
            with P.scope():
                QT = io.scratch("qtd", [D, T], BF16)
                Kz = [P.ssb([128, 4, TT], BF16, f"Kz{k}") for k in range(2)]
                VA = P.ssb([128, 34, 390], BF16, "VA")
                stage_qkv(P, io, G, hb, QT, Kz, VA)
                stage_attn(P, io, G, QT, Kz, VA, xa)
            if stages == "l1pre":
                outs["y"] = io.output("y", [D, TT])
                with P.phase("copyout"):
                    P.dma("sync", outs["y"], xa, writes=["o"], sem="cpa")
            else:
                stage_norm(P, io, G, "n_mlp1", xa, LAT_TILES, ms(1, 1, 0), ms(1, 1, 1), hbdst, BF16)
                stage_mlp(P, io, G, 1, LAT_TILES, xa, hb)
                outs["y"] = io.output("y", [D, T])
                fin = [G["vec"][:, 4, c:c + 1] for c in range(8)]
                stage_norm(P, io, G, "final", xa, LAT_TILES, lambda ic: fin, lambda ic: None,
                           lambda c0, tw, ic: fm(outs["y"][:, c0:c0 + tw]), F32)
    if stages == "t_rwkv":
        hp = io.scratch("hp", [D, 4608], F32)
        S = rw_scratch(io)
        with P.phase("zpad"):
            z = P.sb([128, 8, 64], F32)
            P.memset("vector", z[:], 0.0, ["z"])
            for k, o in enumerate((0, 64 + T, 4224, 4288 + C)):
                P.dma("sync", fm(hp[:, o:o + 64]), z[:], reads=["z"], writes=[("hpz", k)], sem=f"z{k}")
        def hdst(c0, tw, ic):
            o = 4288 if ic else 64 + c0
            return fm(hp[:, o:o + tw])
        stage_norm(P, io, G, "n_mix0", io["xin"], ALL_TILES,
                   lambda ic: mod_scalars(G, 0, 0, ic)[0], lambda ic: mod_scalars(G, 0, 0, ic)[1], hdst, F32)
        stage_rwkv1(P, io, G, hp, S)
        stage_rwkv2(P, io, G, S, io["xin"], xa)
        outs["y"] = io.output("y", [D, TT])
        with P.phase("copyout"):
            P.dma("sync", outs["y"], xa, writes=["o"], sem="cpa")
    P.close()
    return nc, io.used, list(outs.keys()), P


def fmv(v):
    return np.ascontiguousarray(np.asarray(v, np.float32).reshape(8, 128).T)


def host_consts():
    c = {}
    c["c_ident"] = np.eye(128, dtype=np.float32)
    c["c_ones"] = np.ones((128, 128), np.float32)
    blk = np.zeros((128, 128), np.float32)
    blk[:64, :64] = 1
    blk[64:, 64:] = 1
    c["c_bones"] = blk
    i = np.arange(64)
    us = (i[:, None] < i[None, :]).astype(np.float32)
    ui = (i[:, None] <= i[None, :]).astype(np.float32)
    m = np.zeros((128, 4, 128), np.float32)
    for k, mk in enumerate([us, ui, us.T, ui.T]):
        m[:64, k, :64] = mk
        m[64:, k, 64:] = mk
    c["c_masks"] = m
    Pm = np.zeros((128, 128), np.float32)
    for d in range(128):
        if d % 32 < 16:
            Pm[d, d + 16] = -1.0
        else:
            Pm[d, d - 16] = 1.0
    c["c_perm"] = np.ascontiguousarray(Pm.T)
    sel = np.zeros((128, 2, 128), np.float32)
    sel[64, 0, :] = 1.0
    sel[63, 1, :] = 1.0
    c["c_sel"] = sel
    rm = np.ones((128, 256), np.float32)
    rm[:, ::64] = 0
    c["c_rmask"] = rm
    t = np.arange(T)
    row = (t // 64).astype(np.float32)
    col = (t % 64).astype(np.float32)
    freqs = (np.float32(10000.0) ** (-np.arange(0, 32, 2, dtype=np.float32) / np.float32(32))).astype(np.float32)
    ang = np.zeros((64, T), np.float32)
    for d in range(64):
        pos = row if d < 32 else col
        ang[d] = pos * freqs[d % 16]
    c["cosT"] = np.ascontiguousarray(np.concatenate([np.cos(ang), np.cos(ang)], 0).astype(np.float32))
    c["sinT"] = np.ascontiguousarray(np.concatenate([np.sin(ang), np.sin(ang)], 0).astype(np.float32))
    return c


def host_inputs(inp, b):
    f = lambda k: np.asarray(inp[k], np.float32)
    d = {}
    d["xin"] = np.ascontiguousarray(np.concatenate([f("x")[b].T, f("ctx")[b].T], axis=1))
    d["cvec"] = np.ascontiguousarray(np.stack([fmv(f("c")[b]), fmv(f("c_ctx"))], axis=-1))
    return d


def host_shared(inp):
    f = lambda k: np.asarray(inp[k], np.float32)
    s = dict(host_consts())
    s["w_mod"] = f("w_mod")
    s["b_mod"] = f("b_mod")
    vl = [f("norm_mix")[0], f("norm_mix")[1], f("norm_mlp")[0], f("norm_mlp")[1], f("final_norm")]
    vl += [f("rwkv_mu")[0, j] for j in range(6)]
    vl += [f("rwkv_w0")[0, 0], f("rwkv_w0")[0, 1], f("rwkv_a0")[0, 0], f("rwkv_a0")[0, 1]]
    vl += [f("rwkv_k_k")[0], f("rwkv_k_a")[0], np.zeros(D, np.float32), f("rwkv_r_k")[0].reshape(-1)]
    vl += [np.tile(f("attn_q_norm")[0], 16), np.tile(f("attn_k_norm")[0], 16)]
    assert len(vl) == NV
    s["vecs"] = np.ascontiguousarray(np.stack([fmv(v) for v in vl], axis=1))
    s["mlp_w1"] = f("mlp_w1")
    s["mlp_w2"] = f("mlp_w2")
    for k in ("wr", "wk", "wv", "wo", "w1", "w2", "a1", "a2", "g1", "g2"):
        s["rwkv_" + k] = f("rwkv_" + k)[0]
    lw = f("rwkv_ln_w")[0].reshape(8, 2, 64)
    lb = f("rwkv_ln_b")[0].reshape(8, 2, 64)
    s["lnw_st"] = np.ascontiguousarray(np.repeat(lw.transpose(1, 0, 2), 64, axis=0))
    s["lnb_st"] = np.ascontiguousarray(np.repeat(lb.transpose(1, 0, 2), 64, axis=0))
    wqkv = f("attn_wqkv")[0]
    s["attn_wq"] = np.ascontiguousarray(wqkv[:, :1024])
    wk = wqkv[:, 1024:1280].reshape(D, 4, 64)
    s["attn_wkd"] = np.ascontiguousarray(np.concatenate([wk, wk], axis=2).reshape(D, 512))
    s["attn_wv"] = np.ascontiguousarray(wqkv[:, 1280:1536])
    s["attn_wo"] = f("attn_wo")[0]
    return s


_CACHE = {}


def kernel(**inputs):
    if "prog" not in _CACHE:
        _CACHE["prog"] = build("all")
    nc, used, outnames, _ = _CACHE["prog"]
    shared = host_shared(inputs)
    in_maps = []
    for b in range(NCORES):
        hi = host_inputs(inputs, b)
        hi.update(shared)
        in_maps.append({k: hi[k] for k in used})
    res = run_bass_kernel_spmd(nc, in_maps, core_ids=list(range(NCORES)))
    out = np.stack([np.ascontiguousarray(res.results[b]["y"].T) for b in range(NCORES)], axis=0)
    return out.astype(np.float32)
```

```python
from contextlib import ExitStack, contextmanager
import re as re_mod
import numpy as np
import concourse.bass as bass
import concourse.mybir as mybir
from concourse.bass_utils import run_bass_kernel_spmd

F32 = mybir.dt.float32
BF16 = mybir.dt.bfloat16
AF = mybir.ActivationFunctionType
ALU = mybir.AluOpType
AX = mybir.AxisListType

D = 1024
T = 4096
C = 256
TT = T + C
NCORES = 8
C0 = float(np.exp(-0.5))
NV = 21
ENGS = ("tensor", "vector", "scalar", "gpsimd", "sync")


class Prog:
    def __init__(self, nc):
        self.nc = nc
        self.ges = ExitStack()
        self.sems = {}
        self.cnt = {}
        self.dpool = {False: [], True: []}
        self.seen = {e: {} for e in ENGS}
        self.n = 0
        self.pes = None
        self.total_ops = 0

    def _alloc(self, es, fn, shape, dtype, name):
        self.n += 1
        return es.enter_context(fn(name or f"t{self.n}", list(shape), dtype))

    def gsb(self, shape, dtype, name=None):
        return self._alloc(self.ges, self.nc.sbuf_tensor, shape, dtype, name)

    def sb(self, shape, dtype, name=None):
        return self._alloc(self.pes, self.nc.sbuf_tensor, shape, dtype, name)

    @contextmanager
    def scope(self):
        self.ses = ExitStack()
        yield self
        self.ses.close()
        self.ses = None

    def ssb(self, shape, dtype, name=None):
        return self._alloc(self.ses, self.nc.sbuf_tensor, shape, dtype, name)

    def ps(self, shape, dtype, name=None):
        return self._alloc(self.pes, self.nc.psum_tensor, shape, dtype, name)

    @contextmanager
    def phase(self, name):
        self.ops = []
        self.last_w = {}
        self.readers = {}
        self.last_dma = {}
        self.pes = ExitStack()
        self.pname = name
        yield self
        self._emit()
        self.pes.close()
        self.pes = None

    _PSUM_RE = re_mod.compile(r"^(pp|pa|pb|pq|pf|ps\w*|pX|py|pS|ptr|pw)\d*$")

    ns = None
    ns_set = frozenset()

    def _deps(self, reads, writes):
        if self.ns is not None:
            reads = tuple((r, self.ns) if r in self.ns_set else r for r in reads)
            writes = tuple((w, self.ns) if w in self.ns_set else w for w in writes)
        extra = tuple(r for r in reads if isinstance(r, str) and self._PSUM_RE.match(r) and r not in writes)
        if extra:
            writes = tuple(writes) + extra
        deps = {}
        for r in reads:
            if r in self.last_w:
                deps.setdefault(self.last_w[r], set()).add("RAW")
        for w in writes:
            if w in self.last_w:
                deps.setdefault(self.last_w[w], set()).add("WAW")
            for rd in self.readers.get(w, ()):
                deps.setdefault(rd, set()).add("WAR")
        idx = len(self.ops)
        for r in reads:
            self.readers.setdefault(r, []).append(idx)
        for w in writes:
            self.last_w[w] = idx
            self.readers[w] = []
        return deps

    def op(self, eng, fn, reads=(), writes=()):
        deps = self._deps(tuple(reads), tuple(writes))
        self.ops.append(dict(eng=eng, fn=fn, deps=deps, dma=None))
        return len(self.ops) - 1

    def dma(self, queue, out, in_, reads=(), writes=(), sem=None):
        deps = self._deps(tuple(reads), tuple(writes))
        prev = self.last_dma.get(sem)
        if prev is not None:
            deps.setdefault(prev, set()).add("SER")
        idx = len(self.ops)
        self.last_dma[sem] = idx
        self.ops.append(dict(eng=queue, fn=lambda e: e.dma_start(out=out, in_=in_), deps=deps, dma=sem))
        return idx

    def _emit(self):
        nc = self.nc
        ops = self.ops
        if self.last_dma:
            ops.append(dict(eng="sync", fn=None, deps={i: {"FIN"} for i in self.last_dma.values()}, dma=None))
        self.total_ops += len(ops)

        def needs_wait(x, d, kinds):
            if d["dma"] is not None or x["dma"] is not None:
                return True
            if d["eng"] != x["eng"]:
                return True
            if x["eng"] == "tensor":
                return False
            return bool(kinds & {"RAW", "FIN"})

        signal = [False] * len(ops)
        for x in ops:
            for di, kinds in x["deps"].items():
                d = ops[di]
                if d["dma"] is None and needs_wait(x, d, kinds):
                    signal[di] = True
        dkeys = {}
        nk = {False: 0, True: 0}
        for o in ops:
            if o["dma"] is not None and o["dma"] not in dkeys:
                sw = o["eng"] == "gpsimd"
                dkeys[o["dma"]] = (sw, nk[sw])
                nk[sw] += 1
        for sw in (False, True):
            while len(self.dpool[sw]) < nk[sw]:
                h = self.ges.enter_context(nc.semaphore(f"dq{int(sw)}_{len(self.dpool[sw])}"))
                self.dpool[sw].append([h, 0])
        for e in ENGS:
            if e not in self.sems:
                self.sems[e] = self.ges.enter_context(nc.semaphore(f"e_{e}"))
        token = [None] * len(ops)
        for i, o in enumerate(ops):
            if o["dma"] is not None:
                dk = dkeys[o["dma"]]
                slot = self.dpool[dk[0]][dk[1]]
                slot[1] += 16
                token[i] = (("d", dk), slot[1])
            elif signal[i]:
                self.cnt[o["eng"]] = self.cnt.get(o["eng"], 0) + 1
                token[i] = (("e", o["eng"]), self.cnt[o["eng"]])
        per_eng = {e: [] for e in ENGS}
        for i, o in enumerate(ops):
            per_eng[o["eng"]].append(i)

        def semh(key):
            return self.dpool[key[1][0]][key[1][1]][0] if key[0] == "d" else self.sems[key[1]]

        def run(engname, eng):
            seen = self.seen[engname]
            for i in per_eng[engname]:
                o = ops[i]
                waits = {}
                for di, kinds in o["deps"].items():
                    d = ops[di]
                    if not needs_wait(o, d, kinds):
                        continue
                    key, val = token[di]
                    if waits.get(key, 0) < val:
                        waits[key] = val
                for key, val in waits.items():
                    if seen.get(key, 0) >= val:
                        continue
                    seen[key] = val
                    eng.wait_ge(semh(key), val)
                if o["fn"] is None:
                    continue
                ins = o["fn"](eng)
                if o["dma"] is not None:
                    ins.then_inc(semh(token[i][0]), 16)
                elif signal[i]:
                    ins.then_inc(self.sems[engname], 1)

        with nc.Block() as block:
            @block.sync
            def _(e):
                run("sync", e)

            @block.tensor
            def _(e):
                run("tensor", e)

            @block.vector
            def _(e):
                run("vector", e)

            @block.scalar
            def _(e):
                run("scalar", e)

            @block.gpsimd
            def _(e):
                run("gpsimd", e)

    def close(self):
        self.ges.close()

    def mm(self, out, lhsT, rhs, start, stop, r, w):
        self.op("tensor", lambda e: e.matmul(out, lhsT=lhsT, rhs=rhs, start=start, stop=stop), r, w)

    def tr(self, out, in_, ident, r, w):
        self.op("tensor", lambda e: e.transpose(out, in_, ident), r, w)

    def tt(self, eng, out, in0, in1, op, r, w):
        self.op(eng, lambda e: e.tensor_tensor(out=out, in0=in0, in1=in1, op=op), r, w)

    def ts(self, eng, out, in0, s1, s2, op0, op1, r, w):
        if op1 is None:
            self.op(eng, lambda e: e.tensor_scalar(out=out, in0=in0, scalar1=s1, scalar2=None, op0=op0), r, w)
        else:
            self.op(eng, lambda e: e.tensor_scalar(out=out, in0=in0, scalar1=s1, scalar2=s2, op0=op0, op1=op1), r, w)

    def stt(self, out, in0, scalar, in1, op0, op1, r, w):
        self.op("vector", lambda e: e.scalar_tensor_tensor(out=out, in0=in0, scalar=scalar, in1=in1, op0=op0, op1=op1), r, w)

    def act(self, out, in_, func, r, w, bias=None, scale=None):
        kw = {}
        if bias is not None:
            kw["bias"] = bias
        if scale is not None:
            kw["scale"] = scale
        self.op("scalar", lambda e: e.activation(out=out, in_=in_, func=func, **kw), r, w)

    def cp(self, eng, out, in_, r, w):
        if eng == "scalar":
            self.op(eng, lambda e: e.activation(out=out, in_=in_, func=AF.Copy), r, w)
        else:
            self.op(eng, lambda e: e.tensor_copy(out=out, in_=in_), r, w)

    def memset(self, eng, ap, val, w):
        self.op(eng, lambda e: e.memset(ap, val), (), w)


def fm(ap2d):
    return ap2d.rearrange("(c p) n -> p c n", p=128)


LAT_TILES = [(i * 512, 512, False) for i in range(8)]
ALL_TILES = LAT_TILES + [(T, 256, True)]


def stage_init(P, io, G):
    nc = P.nc
    G["identf"] = P.gsb([128, 128], F32, "identf")
    G["identb"] = P.gsb([128, 128], BF16, "identb")
    G["onesb"] = P.gsb([128, 128], BF16, "onesb")
    G["bones"] = P.gsb([128, 128], BF16, "bones")
    G["masks"] = P.gsb([128, 4, 128], BF16, "masks")
    G["perm"] = P.gsb([128, 128], BF16, "perm")
    G["rmask"] = P.gsb([128, 256], F32, "rmask")
    G["vec"] = P.gsb([128, NV, 8], F32, "vec")
    G["modv"] = P.gsb([128, 2, 6, 8, 2], F32, "modv")
    G["gg"] = P.gsb([128, 2, 2, 8, 2], F32, "gg")
    with P.phase("init"):
        P.dma("sync", G["identf"][:], io["c_ident"], writes=["identf"], sem="identf")
        P.dma("sync", G["rmask"][:], io["c_rmask"], writes=["rmask"], sem="rmask")
        P.dma("sync", G["vec"][:], io["vecs"], writes=["vec"], sem="vec")
        P.dma("gpsimd", G["identb"][:], io["c_ident"], writes=["identb"], sem="identb")
        P.dma("gpsimd", G["onesb"][:], io["c_ones"], writes=["onesb"], sem="onesb")
        P.dma("gpsimd", G["bones"][:], io["c_bones"], writes=["bones"], sem="bones")
        P.dma("gpsimd", G["masks"][:], io["c_masks"], writes=["masks"], sem="masks")
        P.dma("gpsimd", G["perm"][:], io["c_perm"], writes=["perm"], sem="perm")
        vec = G["vec"]
        P.ts("vector", vec[:, 17, :], vec[:, 16, :], -1.0, 1.0, ALU.mult, ALU.add, ["vec"], ["vec"])
        sv = P.sb([128, 8, 2], F32)
        svs = P.sb([128, 8, 2], F32)
        P.dma("sync", sv[:], io["cvec"], writes=["sv"], sem="sv")
        P.act(svs[:], sv[:], AF.Silu, ["sv"], ["svs"])
        brow = P.sb([2, 2 * 6144], F32)
        row = P.sb([2, 2 * 6144], F32)
        P.dma("sync", brow[:], io["b_mod"].rearrange("l n -> (l n)").partition_broadcast(2), writes=["brow"], sem="brow")
        wt = [P.sb([128, 8, 512], F32) for _ in range(2)]
        psr = [P.ps([128, 512], F32) for _ in range(2)]
        pst = P.ps([128, 512], F32)
        k = 0
        for l in range(2):
            for nb in range(12):
                b = k % 2
                k += 1
                P.dma("sync", wt[b][:], fm(io["w_mod"][l, :, nb * 512:(nb + 1) * 512]), writes=[f"wt{b}"], sem=f"wt{b}")
                for c in range(8):
                    P.mm(psr[b][0:2, :], svs[:, c, :], wt[b][:, c, :], c == 0, c == 7, ["svs", f"wt{b}"], [f"psr{b}"])
                o = l * 6144 + nb * 512
                P.tt("vector", row[:, o:o + 512], psr[b][0:2, :], brow[:, o:o + 512], ALU.add, [f"psr{b}", "brow"], ["row"])
        for l in range(2):
            for blk in range(48):
                o = l * 6144 + blk * 128
                P.tr(pst[:, l * 96 + blk * 2:l * 96 + blk * 2 + 2], row[0:2, o:o + 128], G["identf"][0:2, 0:2], ["row", "identf"], ["pst"])
        P.cp("vector", G["modv"][:].rearrange("p l m c j -> p (l m c j)"), pst[:, 0:192], ["pst"], ["modv"])
        modv, gg = G["modv"], G["gg"]
        for l in range(2):
            for kind in range(2):
                sc = modv[:, l, 1 + 3 * kind, :, :]
                nv = vec[:, (0 if kind == 0 else 2) + l, :].unsqueeze(2).broadcast_to([128, 8, 2])
                P.ts("vector", gg[:, l, kind, :, :], sc, 1.0, None, ALU.add, None, ["modv"], ["gg"])
                P.tt("vector", gg[:, l, kind, :, :], gg[:, l, kind, :, :], nv, ALU.mult, ["gg", "vec"], ["gg"])


def mod_scalars(G, l, kind, isctx):
    j = 1 if isctx else 0
    gains = [G["gg"][:, l, kind, c, j:j + 1] for c in range(8)]
    shifts = [G["modv"][:, l, 3 * kind, c, j:j + 1] for c in range(8)]
    gates = [G["modv"][:, l, 3 * kind + 2, c, j:j + 1] for c in range(8)]
    return gains, shifts, gates


def stage_norm(P, io, G, name, src, tiles, gains_fn, shifts_fn, dst_fn, out_dtype):
    with P.phase(name):
        xt = [P.sb([128, 8, 512], F32) for _ in range(2)]
        sq = P.sb([128, 8, 512], BF16)
        lnv = P.sb([128, 512], F32)
        rstd = P.sb([128, 512], F32)
        tmp = [P.sb([128, 512], F32) for _ in range(2)]
        ho = [P.sb([128, 8, 512], out_dtype) for _ in range(2)]
        ps = [P.ps([128, 512], F32) for _ in range(2)]

        def load(i):
            c0, tw, _ = tiles[i]
            b = i % 2
            P.dma("sync", xt[b][:, :, :tw], fm(src[:, c0:c0 + tw]), writes=[f"xt{b}"], sem=f"xt{b}")

        load(0)
        for i, (c0, tw, isctx) in enumerate(tiles):
            b = i % 2
            if i + 1 < len(tiles):
                load(i + 1)
            gains = gains_fn(isctx)
            shifts = shifts_fn(isctx)
            P.act(sq[:, :, :tw], xt[b][:, :, :tw], AF.Square, [f"xt{b}"], ["sq"])
            for c in range(8):
                P.mm(ps[b][:, :tw], G["onesb"][:], sq[:, c, :tw], c == 0, c == 7, ["sq", "onesb"], [f"ps{b}"])
            P.act(lnv[:, :tw], ps[b][:, :tw], AF.Ln, [f"ps{b}"], ["lnv"], bias=1e-6, scale=1.0 / D)
            P.act(rstd[:, :tw], lnv[:, :tw], AF.Exp, ["lnv"], ["rstd"], scale=-0.5)
            for c in range(8):
                if shifts is None:
                    P.stt(ho[b][:, c, :tw], xt[b][:, c, :tw], gains[c], rstd[:, :tw], ALU.mult, ALU.mult,
                          [f"xt{b}", "rstd", "vec", "gg"], [f"ho{b}"])
                else:
                    t = tmp[c % 2]
                    P.stt(t[:, :tw], xt[b][:, c, :tw], gains[c], rstd[:, :tw], ALU.mult, ALU.mult,
                          [f"xt{b}", "rstd", "vec", "gg"], [f"tmp{c % 2}"])
                    P.act(ho[b][:, c, :tw], t[:, :tw], AF.Identity, [f"tmp{c % 2}", "modv"], [f"ho{b}"], bias=shifts[c])
            P.dma("sync", dst_fn(c0, tw, isctx), ho[b][:, :, :tw], reads=[f"ho{b}"], writes=[("dst", i)], sem=f"ho{b}")


def stage_mlp(P, io, G, l, tiles, xa, hb):
    for half in range(2):
        with P.phase(f"mlp{l}{half}"):
            w1 = P.sb([128, 8, 2048], BF16)
            w2 = P.sb([128, 16, 1024], BF16)
            for q in range(2):
                P.dma("gpsimd", w1[:, :, q * 1024:(q + 1) * 1024],
                      fm(io["mlp_w1"][l, :, half * 2048 + q * 1024: half * 2048 + (q + 1) * 1024]), writes=["w1"], sem=f"w1{q}")
                P.dma("gpsimd", w2[:, q * 8:(q + 1) * 8, :],
                      io["mlp_w2"][l, half * 2048 + q * 1024: half * 2048 + (q + 1) * 1024, :].rearrange("(f p) n -> p f n", p=128),
                      writes=["w2"], sem=f"w2{q}")
            xt = [P.sb([128, 8, 512], F32) for _ in range(2)]
            ht = [P.sb([128, 8, 512], BF16) for _ in range(2)]
            h1 = P.sb([128, 16, 512], BF16)
            r1 = [P.sb([128, 512], F32) for _ in range(2)]
            ps = [P.ps([128, 512], F32) for _ in range(4)]

            def load(i):
                c0, tw, _ = tiles[i]
                b = i % 2
                P.dma("sync", ht[b][:, :, :tw], fm(hb[:, c0:c0 + tw]), writes=[f"ht{b}"], sem=f"ht{b}")
                P.dma("sync", xt[b][:, :, :tw], fm(xa[:, c0:c0 + tw]), reads=[("xa", i)], writes=[f"xt{b}"], sem=f"xt{b}")

            load(0)
            for i, (c0, tw, isctx) in enumerate(tiles):
                b = i % 2
                if i + 1 < len(tiles):
                    load(i + 1)
                _, _, gates = mod_scalars(G, l, 1, isctx)
                for fc in range(16):
                    pb = fc % 2
                    for c in range(8):
                        P.mm(ps[pb][:, :tw], w1[:, c, fc * 128:(fc + 1) * 128], ht[b][:, c, :tw], c == 0, c == 7,
                             ["w1", f"ht{b}"], [f"ps{pb}"])
                    P.act(r1[pb][:, :tw], ps[pb][:, :tw], AF.Relu, [f"ps{pb}"], [f"r1{pb}"])
                    P.tt("gpsimd", h1[:, fc, :tw], r1[pb][:, :tw], r1[pb][:, :tw], ALU.mult, [f"r1{pb}"], [("h1", fc)])
                for oc in range(8):
                    pb = 2 + oc % 2
                    for fc in range(16):
                        P.mm(ps[pb][:, :tw], w2[:, fc, oc * 128:(oc + 1) * 128], h1[:, fc, :tw], fc == 0, fc == 15,
                             ["w2", ("h1", fc)], [f"ps{pb}"])
                    P.stt(xt[b][:, oc, :tw], ps[pb][:, :tw], gates[oc], xt[b][:, oc, :tw], ALU.mult, ALU.add,
                          [f"ps{pb}", f"xt{b}", "modv"], [f"xt{b}"])
                P.dma("sync", fm(xa[:, c0:c0 + tw]), xt[b][:, :, :tw], reads=[f"xt{b}"], writes=[("xa", i)], sem=f"xt{b}")


RW_ORDER1 = [(True, 0)] + [(False, i) for i in range(16)]
RW_ORDER2 = [(True, 0)] + [(False, i) for i in range(15, -1, -1)]


def rw_scratch(io):
    S = {}
    S["yp"] = io.scratch("rw_yp", [17, 8, 128, 256], F32)
    S["sadd"] = io.scratch("rw_sadd", [17, 8, 128, 256], F32)
    S["vst"] = io.scratch("rw_vst", [17, 8, 128, 256], F32)
    S["gst"] = io.scratch("rw_gst", [17, 8, 128, 256], F32)
    S["gyb"] = io.scratch("rw_gyb", [17, 8, 128, 512], BF16)
    S["gsb"] = io.scratch("rw_gsb", [17, 8, 128, 512], BF16)
    S["gamb"] = io.scratch("rw_gamb", [17, 128, 32], F32)
    S["bon"] = io.scratch("rw_bon", [17, 128, 32], F32)
    S["ops"] = io.scratch("rw_ops", [17, 8, 128, 2048], BF16)
    S["vb"] = io.scratch("rw_vb", [17, 8, 128, 256], BF16)
    S["gam"] = io.scratch("rw_gam", [17, 8, 128, 8], F32)
    return S


def stage_rwkv1(P, io, G, hp, S, dbg=None):
    vec, masks, identb, identf, bones, onesb, rmask = (G[k] for k in ("vec", "masks", "identb", "identf", "bones", "onesb", "rmask"))
    with P.phase("rwkv1"):
        wr = P.sb([128, 8, 1024], BF16)
        wk = P.sb([128, 8, 1024], BF16)
        wv = P.sb([128, 8, 1024], BF16)
        for w, nm in ((wr, "rwkv_wr"), (wk, "rwkv_wk"), (wv, "rwkv_wv")):
            P.dma("gpsimd", w[:], fm(io[nm]), writes=[nm], sem=nm)
        lw1 = P.sb([128, 8, 128], BF16)
        la1 = P.sb([128, 8, 128], BF16)
        g1 = P.sb([128, 8, 128], BF16)
        for d in range(2):
            P.dma("gpsimd", lw1[:, :, d * 64:(d + 1) * 64], io["rwkv_w1"][d].rearrange("(c p) j -> p c j", p=128), writes=["lw1"], sem=f"lw1{d}")
            P.dma("gpsimd", la1[:, :, d * 64:(d + 1) * 64], io["rwkv_a1"][d].rearrange("(c p) j -> p c j", p=128), writes=["la1"], sem=f"la1{d}")
        P.dma("gpsimd", g1[:], io["rwkv_g1"].rearrange("(c p) j -> p c j", p=128), writes=["g1"], sem="g1")
        w2s = P.sb([128, 1024], BF16)
        a2s = P.sb([128, 1024], BF16)
        g2 = P.sb([128, 1024], BF16)
        P.dma("gpsimd", w2s[:], io["rwkv_w2"].rearrange("d j f -> (d j) f"), writes=["w2s"], sem="w2s")
        P.dma("gpsimd", a2s[:], io["rwkv_a2"].rearrange("d j f -> (d j) f"), writes=["a2s"], sem="a2s")
        P.dma("gpsimd", g2[:], io["rwkv_g2"], writes=["g2"], sem="g2")

        hh = P.sb([128, 8, 384], F32)
        xx = P.sb([128, 8, 256], F32)
        xr = P.sb([128, 8, 256], BF16)
        xk = P.sb([128, 8, 256], BF16)
        xv = P.sb([128, 8, 256], BF16)
        xrot = P.sb([128, 8, 256], BF16)
        lwt = P.sb([128, 256], BF16)
        lat = P.sb([128, 256], BF16)
        sg = P.sb([128, 256], BF16)
        f32t = {}
        for nm in ("r", "k", "sw0", "sw1", "ag0", "ag1", "kq", "lnv", "rs", "kkn", "fac", "kd0", "kd1", "b0", "b1",
                   "L", "Lx", "Lb", "E1", "E2", "E3", "ks"):
            f32t[nm] = P.sb([128, 256], F32, "t_" + nm)
        sqb = P.sb([128, 256], BF16)
        RK = P.sb([128, 4, 2, 64], BF16)
        VTbd = P.sb([128, 4, 128], F32)
        GTbd = P.sb([128, 4, 128], F32)
        Vf = P.sb([128, 4, 64], F32)
        Gf = P.sb([128, 4, 64], F32)
        YPs = P.sb([128, 4, 64], F32)
        SAs = P.sb([128, 4, 64], F32)
        gamb_t = P.sb([128, 8, 4], F32)
        bon_t = P.sb([128, 8, 4], F32)
        Sf = P.sb([128, 8, 64], BF16)
        ARq = [[P.sb([128, 4, 2, 128], BF16, f"AR{q}{d}") for d in range(2)] for q in range(2)]
        KTq = [[P.sb([128, 4, 128], BF16, f"KT{q}{d}") for d in range(2)] for q in range(2)]
        BTq = [[P.sb([128, 4, 128], BF16, f"BT{q}{d}") for d in range(2)] for q in range(2)]
        Vbq = [P.sb([128, 4, 64], BF16, f"Vb{q}") for q in range(3)]
        gamq = [[P.sb([128, 4], F32, f"gam{q}{d}") for d in range(2)] for q in range(3)]
        inv = []
        for d in range(2):
            st = {}
            for nm, shp in (("Atok", [128, 4, 128]), ("Btok", [128, 4, 128]), ("MQ", [128, 4, 256]), ("MWa", [128, 4, 2, 128]),
                            ("MWb", [128, 4, 2, 128]), ("MTa", [128, 4, 128]), ("MTb", [128, 4, 128])):
                st[nm] = P.sb(shp, BF16, f"i{d}_{nm}")
            inv.append(st)
        fin = []
        for q in range(2):
            row = []
            for d in range(2):
                st = {}
                for nm, shp in (("Ktok", [128, 4, 128]), ("NP", [128, 4, 256]), ("XW", [128, 4, 256]), ("NVb", [128, 4, 64]),
                                ("GY", [128, 4, 128]), ("GS", [128, 4, 128])):
                    st[nm] = P.sb(shp, BF16, f"f{q}{d}_{nm}")
                row.append(st)
            fin.append(row)
        ppt = [P.ps([128, 512], F32) for _ in range(2)]
        pp = [t_[:, 0:256] for t_ in ppt]
        pf = P.ps([128, 512], F32)
        pb = [P.ps([128, 512], F32) for _ in range(5)]
        cnt = {"pp": 0, "pb": 0}
        nmod = {"pp": 2, "pb": 5}

        def nxt(kind):
            i = cnt[kind] % nmod[kind]
            cnt[kind] += 1
            return i

        for q in range(2):
            for d in range(2):
                P.memset("gpsimd", ARq[q][d][:], 0.0, [f"AR{q}{d}"])
                P.memset("gpsimd", KTq[q][d][:], 0.0, [f"KT{q}{d}"])
                P.memset("gpsimd", BTq[q][d][:], 0.0, [f"BT{q}{d}"])
        P.memset("gpsimd", RK[:], 0.0, ["RK"])
        P.memset("gpsimd", VTbd[:], 0.0, ["VTbd"])
        P.memset("gpsimd", GTbd[:], 0.0, ["GTbd"])
        P.memset("gpsimd", Sf[:], 0.0, [("Sf", p) for p in range(8)])

        def v3(ap):
            return ap.rearrange("p (u s) -> p u s", s=64)

        def u128(ap):
            return ap.rearrange("p (u x) -> p u x", x=128)

        def load_hh(ti):
            isctx, idx = RW_ORDER1[ti]
            off = 4288 if isctx else 64 + 256 * idx
            P.dma("sync", hh[:], fm(hp[:, off - 64: off + 320]), writes=["hh"], sem="hh")

        def proj8(w_cols_fn, xb, bn, extra_r):
            i = nxt("pp")
            for c in range(8):
                P.mm(pp[i], w_cols_fn(c), xb[:, c, :], c == 0, c == 7, [(bn, c)] + extra_r, [f"pp{i}"])
            return i

        def tprep(ti):
            isctx, idx = RW_ORDER1[ti]
            hc = hh[:, :, 64:320]
            XXW = [("xx", c) for c in range(8)]
            if not isctx:
                h4 = hh[:, :, 64:320].rearrange("p c (r w) -> p c r w", w=64)
                x4 = xx[:].rearrange("p c (r w) -> p c r w", w=64)
                P.tt("vector", x4[:, 0:2, :, 1:64], h4[:, 0:2, :, 0:63], h4[:, 0:2, :, 1:64], ALU.subtract, ["hh"], XXW[0:2])
                P.ts("gpsimd", x4[:, 0:2, :, 0:1], h4[:, 0:2, :, 0:1], -1.0, 0.0, ALU.mult, ALU.add, ["hh"], [("xxe", 0)])
                P.tt("vector", x4[:, 2:4, :, 0:63], h4[:, 2:4, :, 1:64], h4[:, 2:4, :, 0:63], ALU.subtract, ["hh"], XXW[2:4])
                P.ts("gpsimd", x4[:, 2:4, :, 63:64], h4[:, 2:4, :, 63:64], -1.0, 0.0, ALU.mult, ALU.add, ["hh"], [("xxe", 1)])
                P.tt("gpsimd", xx[:, 4:6, :], hh[:, 4:6, 0:256], hh[:, 4:6, 64:320], ALU.subtract, ["hh"], XXW[4:6])
                P.tt("gpsimd", xx[:, 6:8, :], hh[:, 6:8, 128:384], hh[:, 6:8, 64:320], ALU.subtract, ["hh"], XXW[6:8])
            else:
                P.tt("vector", xx[:, 0:4, :], hh[:, 0:4, 63:319], hh[:, 0:4, 64:320], ALU.subtract, ["hh"], XXW[0:4] + [("xxe", 0)])
                P.tt("gpsimd", xx[:, 4:8, :], hh[:, 4:8, 65:321], hh[:, 4:8, 64:320], ALU.subtract, ["hh"], XXW[4:8] + [("xxe", 1)])
            yield

            def mk_xj(j, buf, bn):
                for c in range(8):
                    P.stt(buf[:, c, :], xx[:, c, :], vec[:, 5 + j, c:c + 1], hc[:, c, :], ALU.mult, ALU.add,
                          [("xx", c), ("xxe", 0), ("xxe", 1), "hh", "vec"], [(bn, c)])

            mk_xj(1, xrot, "xrot")
            yield
            i = proj8(lambda c: lw1[:, c, :], xrot, "xrot", ["lw1"])
            P.act(lwt[:], pp[i], AF.Tanh, [f"pp{i}"], ["lwt"])
            yield
            mk_xj(4, xrot, "xrot")
            yield
            i = proj8(lambda c: la1[:, c, :], xrot, "xrot", ["la1"])
            P.cp("scalar", lat[:], pp[i], [f"pp{i}"], ["lat"])
            yield
            mk_xj(5, xrot, "xrot")
            yield
            i = proj8(lambda c: g1[:, c, :], xrot, "xrot", ["g1"])
            P.act(sg[:], pp[i], AF.Sigmoid, [f"pp{i}"], ["sg"])
            yield
            mk_xj(0, xr, "xr")
            yield
            mk_xj(2, xk, "xk")
            yield
            mk_xj(3, xv, "xv")
            if ti + 1 < len(RW_ORDER1):
                load_hh(ti + 1)
            yield

        def prep(ti, oc, q, z):
            isctx, idx = RW_ORDER1[ti]
            tg = 16 if isctx else idx
            cs = slice(oc * 128, (oc + 1) * 128)
            t = f32t
            AR, KT, BT, Vb, gam = ARq[q], KTq[q], BTq[q], Vbq[z], gamq[z]
            i = proj8(lambda c: wr[:, c, cs], xr, "xr", ["rwkv_wr"])
            P.cp("scalar", t["r"][:], pp[i], [f"pp{i}"], ["r"])
            i = proj8(lambda c: wk[:, c, cs], xk, "xk", ["rwkv_wk"])
            P.cp("scalar", t["k"][:], pp[i], [f"pp{i}"], ["k"])
            i = proj8(lambda c: wv[:, c, cs], xv, "xv", ["rwkv_wv"])
            vt4 = VTbd[:].rearrange("p u (h s) -> p u h s", h=2)
            for h2 in range(2):
                sl = slice(h2 * 64, (h2 + 1) * 64)
                P.cp("scalar", vt4[sl, :, h2, :], v3(pp[i][sl, :]), [f"pp{i}"], ["VTbd"])
            i = nxt("pp")
            P.mm(pp[i], g2[:, cs], sg[:], True, True, ["g2", "sg"], [f"pp{i}"])
            gt4 = GTbd[:].rearrange("p u (h s) -> p u h s", h=2)
            for h2 in range(2):
                sl = slice(h2 * 64, (h2 + 1) * 64)
                P.cp("scalar", gt4[sl, :, h2, :], v3(pp[i][sl, :]), [f"pp{i}"], ["GTbd"])
            yield
            j = nxt("pb")
            for u in range(4):
                P.tr(pb[j][:, u * 128:(u + 1) * 128], VTbd[:, u, :], identf[:], ["VTbd", "identf"], [f"pb{j}"])
            pv = u128(pb[j][:])
            for h2 in range(2):
                sl = slice(h2 * 64, (h2 + 1) * 64)
                P.cp("scalar", Vf[sl, :, :], pv[sl, :, h2 * 64:(h2 + 1) * 64], [f"pb{j}"], ["Vf"])
            P.cp("gpsimd", Vb[:], Vf[:], ["Vf"], [f"Vb{z}"])
            P.dma("sync", S["vst"][tg, oc].rearrange("p (u s) -> p u s", s=64), Vf[:], reads=["Vf"], writes=[("vst", tg, oc)], sem="Vf")
            j = nxt("pb")
            for u in range(4):
                P.tr(pb[j][:, u * 128:(u + 1) * 128], GTbd[:, u, :], identf[:], ["GTbd", "identf"], [f"pb{j}"])
            pv = u128(pb[j][:])
            for h2 in range(2):
                sl = slice(h2 * 64, (h2 + 1) * 64)
                P.cp("scalar", Gf[sl, :, :], pv[sl, :, h2 * 64:(h2 + 1) * 64], [f"pb{j}"], ["Gf"])
            P.dma("sync", S["gst"][tg, oc].rearrange("p (u s) -> p u s", s=64), Gf[:], reads=["Gf"], writes=[("gst", tg, oc)], sem="Gf")
            yield
            for d in range(2):
                dl = slice(d * 64, (d + 1) * 64)
                i = nxt("pp")
                P.mm(pp[i], w2s[dl, cs], lwt[dl, :], True, True, ["w2s", "lwt"], [f"pp{i}"])
                P.act(t[f"sw{d}"][:], pp[i], AF.Sigmoid, [f"pp{i}", "vec"], [f"sw{d}"], bias=vec[:, 11 + d, oc:oc + 1])
                i = nxt("pp")
                P.mm(pp[i], a2s[dl, cs], lat[dl, :], True, True, ["a2s", "lat"], [f"pp{i}"])
                P.act(t[f"ag{d}"][:], pp[i], AF.Sigmoid, [f"pp{i}", "vec"], [f"ag{d}"], bias=vec[:, 13 + d, oc:oc + 1])
            yield
            P.ts("vector", t["kq"][:], t["k"][:], vec[:, 15, oc:oc + 1], None, ALU.mult, None, ["k", "vec"], ["kq"])
            P.act(sqb[:], t["kq"][:], AF.Square, ["kq"], ["sqb"])
            i = nxt("pp")
            P.mm(pp[i], bones[:], sqb[:], True, True, ["bones", "sqb"], [f"pp{i}"])
            P.act(t["lnv"][:], pp[i], AF.Ln, [f"pp{i}"], ["lnv"], bias=1e-12)
            P.act(t["rs"][:], t["lnv"][:], AF.Exp, ["lnv"], ["rs"], scale=-0.5)
            P.tt("gpsimd", t["kkn"][:], t["kq"][:], t["rs"][:], ALU.mult, ["kq", "rs"], ["kkn"])
            for d in range(2):
                sw, ag, kd, bb = t[f"sw{d}"], t[f"ag{d}"], t[f"kd{d}"], t[f"b{d}"]
                EE = "gpsimd" if d == 0 else "vector"
                P.ts(EE, t["fac"][:], ag[:], vec[:, 16, oc:oc + 1], vec[:, 17, oc:oc + 1], ALU.mult, ALU.add, [f"ag{d}", "vec"], ["fac"])
                P.tt(EE, kd[:], t["k"][:], t["fac"][:], ALU.mult, ["k", "fac"], [f"kd{d}"])
                P.tt(EE, bb[:], t["kkn"][:], ag[:], ALU.mult, ["kkn", f"ag{d}"], [f"b{d}"])
                P.op("vector", lambda e, sw=sw: e.tensor_tensor_scan(out=t["L"][:], data0=rmask[:], data1=sw[:], initial=0.0,
                                                                      op0=ALU.mult, op1=ALU.add), [f"sw{d}", "rmask"], ["L"])
                L3 = v3(t["L"][:])
                if d == 0:
                    P.tt(EE, t["Lx"][:], t["L"][:], sw[:], ALU.subtract, ["L", f"sw{d}"], ["Lx"])
                    Li, Lin = t["L"], "L"
                else:
                    P.tt(EE, v3(t["Lx"][:]), L3[:, :, 63:64].broadcast_to([128, 4, 64]), L3, ALU.subtract, ["L"], ["Lx"])
                    P.tt(EE, t["Lb"][:], t["Lx"][:], sw[:], ALU.add, ["Lx", f"sw{d}"], ["Lb"])
                    Li, Lin = t["Lb"], "Lb"
                P.act(t["E1"][:], Li[:], AF.Exp, [Lin], ["E1"], scale=-C0)
                P.act(t["E3"][:], Li[:], AF.Exp, [Lin], ["E3"], scale=C0)
                P.act(t["E2"][:], t["Lx"][:], AF.Exp, ["Lx"], ["E2"], scale=-C0)
                ar5 = AR[d][:].rearrange("p u a (h s) -> p u a h s", h=2)
                kt4 = KT[d][:].rearrange("p u (h s) -> p u h s", h=2)
                bt4 = BT[d][:].rearrange("p u (h s) -> p u h s", h=2)
                for h2 in range(2):
                    sl = slice(h2 * 64, (h2 + 1) * 64)
                    P.stt(ar5[sl, :, 0, h2, :], v3(t["kkn"][sl, :]), -1.0, v3(t["E2"][sl, :]), ALU.mult, ALU.mult, ["kkn", "E2"], [f"AR{q}{d}"])
                    P.tt(EE, ar5[sl, :, 1, h2, :], v3(t["r"][sl, :]), v3(t["E1"][sl, :]), ALU.mult, ["r", "E1"], [f"AR{q}{d}"])
                    P.tt(EE, kt4[sl, :, h2, :], v3(kd[sl, :]), v3(t["E3"][sl, :]), ALU.mult, [f"kd{d}", "E3"], [f"KT{q}{d}"])
                    P.tt(EE, bt4[sl, :, h2, :], v3(bb[sl, :]), v3(t["E3"][sl, :]), ALU.mult, [f"b{d}", "E3"], [f"BT{q}{d}"])
                E13 = v3(t["E1"][:])
                gsrc = E13[:, :, 63] if d == 0 else E13[:, :, 0]
                P.cp("vector", gam[d][:], gsrc, ["E1"], [f"gam{z}{d}"])
                if d == 1:
                    P.cp("gpsimd", gamb_t[:, oc, :], gam[1][:], [f"gam{z}1"], ["gamb_t"])
                yield
            P.tt("gpsimd", t["ks"][:], t["kd0"][:], t["kd1"][:], ALU.add, ["kd0", "kd1"], ["ks"])
            for h2 in range(2):
                sl = slice(h2 * 64, (h2 + 1) * 64)
                P.stt(RK[sl, :, h2, :], v3(t["r"][sl, :]), vec[sl, 18, oc:oc + 1], v3(t["ks"][sl, :]), ALU.mult, ALU.mult, ["r", "ks", "vec"], ["RK"])
            i = nxt("pp")
            for u in range(4):
                P.mm(pp[i][:, u:u + 1], RK[:, u, :, :].rearrange("p h s -> p (h s)"), onesb[:, 0:1], True, True, ["RK", "onesb"], [f"pp{i}"])
            P.cp("scalar", bon_t[:, oc, :], pp[i][:, 0:4], [f"pp{i}"], ["bon_t"])
            if oc == 7:
                P.dma("sync", S["gamb"][tg], gamb_t[:].rearrange("p a b -> p (a b)"), reads=["gamb_t"], writes=[("gamb", tg)], sem="gamb_t")
                P.dma("sync", S["bon"][tg], bon_t[:].rearrange("p a b -> p (a b)"), reads=["bon_t"], writes=[("bon", tg)], sem="bon_t")
            yield

        def chain(ti, oc, q, d, z):
            AR, KT, BT, Vb = ARq[q][d], KTq[q][d], BTq[q][d], Vbq[z]
            ARn, KTn, BTn, Vbn = f"AR{q}{d}", f"KT{q}{d}", f"BT{q}{d}", f"Vb{z}"
            iv, fn = inv[d], fin[q][d]
            IR = lambda nm: f"i{d}_{nm}"
            FR = lambda nm: f"f{q}{d}_{nm}"
            mS, mC = (0, 2) if d == 0 else (2, 0)
            mSI = masks[:, mS:mS + 2, :].rearrange("p a b -> p (a b)").unsqueeze(1).broadcast_to([128, 4, 256])
            mCb = masks[:, mC, :].unsqueeze(1).broadcast_to([128, 4, 128])
            idb = identb[:].unsqueeze(1).broadcast_to([128, 4, 128])
            for src, srcn, dst, dstn in ((AR[:, :, 0, :], ARn, iv["Atok"], IR("Atok")), (BT[:], BTn, iv["Btok"], IR("Btok")),
                                         (KT[:], KTn, fn["Ktok"], FR("Ktok"))):
                j = nxt("pb")
                pbt = pb[j][:].bitcast(BF16)
                for u in range(4):
                    P.tr(pbt[:, u * 128:(u + 1) * 128], src[:, u, :], identb[:], [srcn, "identb"], [f"pb{j}"])
                P.cp("scalar", dst[:].rearrange("p u x -> p (u x)"), pbt[:, 0:512], [f"pb{j}"], [dstn])
            mSb = masks[:, mS, :].unsqueeze(1).broadcast_to([128, 4, 128])
            mIb = masks[:, mS + 1, :].unsqueeze(1).broadcast_to([128, 4, 128])

            def two_bank(mm_fn):
                j0, j1 = nxt("pb"), nxt("pb")
                for u in range(4):
                    mm_fn(u, pb[j0][:, u * 128:(u + 1) * 128], f"pb{j0}", pb[j1][:, u * 128:(u + 1) * 128], f"pb{j1}")
                return j0, j1

            for lhs, lhsn, dst, dstn in ((BT, BTn, iv["MQ"], IR("MQ")), (KT, KTn, fn["NP"], FR("NP"))):
                def mm_ab(u, o0, n0, o1, n1, lhs=lhs, lhsn=lhsn):
                    P.mm(o0, lhs[:, u, :], AR[:, u, 0, :], True, True, [lhsn, ARn], [n0])
                    P.mm(o1, lhs[:, u, :], AR[:, u, 1, :], True, True, [lhsn, ARn], [n1])
                j0, j1 = two_bank(mm_ab)
                P.tt("vector", dst[:, :, 0:128], u128(pb[j0][:]), mSb, ALU.mult, [f"pb{j0}", "masks"], [dstn])
                P.tt("vector", dst[:, :, 128:256], u128(pb[j1][:]), mIb, ALU.mult, [f"pb{j1}", "masks"], [dstn])
            j = nxt("pb")
            for u in range(4):
                P.mm(pb[j][:, u * 128:(u + 1) * 128], AR[:, u, 0, :], BT[:, u, :], True, True, [ARn, BTn], [f"pb{j}"])
            cur, curn, nx, nxn = iv["MWa"], IR("MWa"), iv["MWb"], IR("MWb")
            P.tt("vector", cur[:, :, 0, :], u128(pb[j][:]), mCb, ALU.mult, [f"pb{j}", "masks"], [curn])
            yield
            j = nxt("pb")
            for u in range(4):
                P.mm(pb[j][:, u * 128:(u + 1) * 128], iv["MQ"][:, u, 0:128], cur[:, u, 0, :], True, True, [IR("MQ"), curn], [f"pb{j}"])
            P.cp("scalar", nx[:, :, 0, :], u128(pb[j][:]), [f"pb{j}"], [nxn])
            P.tt("gpsimd", nx[:, :, 1, :], cur[:, :, 0, :], idb, ALU.add, [curn, "identb"], [nxn])
            j = nxt("pb")
            for u in range(4):
                P.mm(pb[j][:, u * 128:(u + 1) * 128], cur[:, u, 0, :], iv["MQ"][:, u, 0:128], True, True, [IR("MQ"), curn], [f"pb{j}"])
            curT, curTn, nxT, nxTn = iv["MTa"], IR("MTa"), iv["MTb"], IR("MTb")
            P.cp("scalar", curT[:], u128(pb[j][:]), [f"pb{j}"], [curTn])
            cur, curn, nx, nxn = nx, nxn, cur, curn
            yield
            for lev in range(1, 5):
                def mm_lev(u, o0, n0, o1, n1, cur=cur, curn=curn, curT=curT, curTn=curTn):
                    P.mm(o0, curT[:, u, :], cur[:, u, 0, :], True, True, [curTn, curn], [n0])
                    P.mm(o1, curT[:, u, :], cur[:, u, 1, :], True, True, [curTn, curn], [n1])
                j0, j1 = two_bank(mm_lev)
                P.cp("scalar", nx[:, :, 0, :], u128(pb[j0][:]), [f"pb{j0}"], [nxn])
                P.tt("vector", nx[:, :, 1, :], u128(pb[j1][:]), cur[:, :, 1, :], ALU.add, [f"pb{j1}", curn], [nxn])
                j = nxt("pb")
                for u in range(4):
                    P.mm(pb[j][:, u * 128:(u + 1) * 128], cur[:, u, 0, :], curT[:, u, :], True, True, [curn, curTn], [f"pb{j}"])
                P.cp("scalar", nxT[:], u128(pb[j][:]), [f"pb{j}"], [nxTn])
                cur, curn, nx, nxn = nx, nxn, cur, curn
                curT, curTn, nxT, nxTn = nxT, nxTn, curT, curTn
                yield
            j = nxt("pb")
            for u in range(4):
                P.mm(pb[j][:, u * 128:(u + 1) * 128], curT[:, u, :], cur[:, u, 1, :], True, True, [curTn, curn], [f"pb{j}"])
            P.tt("vector", nx[:, :, 1, :], u128(pb[j][:]), cur[:, :, 1, :], ALU.add, [f"pb{j}", curn], [nxn])
            W6, W6n = nx, nxn
            j = nxt("pb")
            for u in range(4):
                P.mm(pb[j][:, u * 64:(u + 1) * 64], fn["NP"][:, u, 0:128], Vb[:, u, :], True, True, [FR("NP"), Vbn], [f"pb{j}"])
            P.cp("scalar", fn["NVb"][:].rearrange("p u x -> p (u x)"), pb[j][:, 0:256], [f"pb{j}"], [FR("NVb")])
            yield

            def mm_d(u, o0, n0, o1, n1):
                P.mm(o0, W6[:, u, 1, :], iv["MQ"][:, u, 128:256], True, True, [W6n, IR("MQ")], [n0])
                P.mm(o1, W6[:, u, 1, :], iv["Btok"][:, u, :], True, True, [W6n, IR("Btok")], [n1])
            j0, j1 = two_bank(mm_d)
            P.cp("scalar", fn["XW"][:, :, 0:128], u128(pb[j0][:]), [f"pb{j0}"], [FR("XW")])
            P.cp("vector", fn["XW"][:, :, 128:256], u128(pb[j1][:]), [f"pb{j1}"], [FR("XW")])
            yield

            def mm_f(u, o0, n0, o1, n1):
                P.mm(o0, iv["Atok"][:, u, :], fn["XW"][:, u, 0:128], True, True, [IR("Atok"), FR("XW")], [n0])
                P.mm(o1, iv["Atok"][:, u, :], fn["XW"][:, u, 128:256], True, True, [IR("Atok"), FR("XW")], [n1])
            j0, j1 = two_bank(mm_f)
            P.tt("vector", fn["GY"][:], u128(pb[j0][:]), AR[:, :, 1, :], ALU.add, [f"pb{j0}", ARn], [FR("GY")])
            P.tt("vector", fn["GS"][:], u128(pb[j1][:]), idb, ALU.add, [f"pb{j1}", "identb"], [FR("GS")])
            yield

        def finish(ti, oc, q, z):
            isctx, idx = RW_ORDER1[ti]
            tg = 16 if isctx else idx
            sf, sb_ = fin[q]
            F0 = lambda nm: f"f{q}0_{nm}"
            F1 = lambda nm: f"f{q}1_{nm}"
            Vb, Vbn, gam = Vbq[z], f"Vb{z}", gamq[z]
            SFR = ("Sf", oc)
            for u in range(4):
                yo = pf[:, u * 64:(u + 1) * 64]
                P.mm(yo, sf["NP"][:, u, 128:256], Vb[:, u, :], True, False, [F0("NP"), Vbn], ["pf"])
                P.mm(yo, sf["XW"][:, u, 0:128], sf["NVb"][:, u, :], False, False, [F0("XW"), F0("NVb")], ["pf"])
                P.mm(yo, sb_["NP"][:, u, 128:256], Vb[:, u, :], False, False, [F1("NP"), Vbn], ["pf"])
                P.mm(yo, sb_["XW"][:, u, 0:128], sb_["NVb"][:, u, :], False, False, [F1("XW"), F1("NVb")], ["pf"])
                P.mm(yo, sf["GY"][:, u, :], Sf[:, oc, :], False, True, [F0("GY"), SFR], ["pf"])
                so = pf[:, 256:320]
                P.mm(so, sf["Ktok"][:, u, :], Vb[:, u, :], True, False, [F0("Ktok"), Vbn], ["pf"])
                P.mm(so, sf["XW"][:, u, 128:256], sf["NVb"][:, u, :], False, False, [F0("XW"), F0("NVb")], ["pf"])
                P.mm(so, sf["GS"][:, u, :], Sf[:, oc, :], False, True, [F0("GS"), SFR], ["pf"])
                P.ts("vector", Sf[:, oc, :], so, gam[0][:, u:u + 1], None, ALU.mult, None, ["pf", f"gam{z}0"], [SFR])
                yield
            P.cp("vector", YPs[:].rearrange("p u x -> p (u x)"), pf[:, 0:256], ["pf"], ["YPs"])
            P.dma("sync", S["yp"][tg, oc], YPs[:].rearrange("p u x -> p (u x)"), reads=["YPs"], writes=[("yp", tg, oc)], sem="YPs")
            j = nxt("pb")
            for u in range(4):
                so = pb[j][:, u * 64:(u + 1) * 64]
                P.mm(so, sb_["Ktok"][:, u, :], Vb[:, u, :], True, False, [F1("Ktok"), Vbn], [f"pb{j}"])
                P.mm(so, sb_["XW"][:, u, 128:256], sb_["NVb"][:, u, :], False, True, [F1("XW"), F1("NVb")], [f"pb{j}"])
            P.cp("scalar", SAs[:].rearrange("p u x -> p (u x)"), pb[j][:, 0:256], [f"pb{j}"], ["SAs"])
            P.dma("sync", S["sadd"][tg, oc], SAs[:].rearrange("p u x -> p (u x)"), reads=["SAs"], writes=[("sadd", tg, oc)], sem="SAs")
            P.dma("sync", S["gyb"][tg, oc], sb_["GY"][:].rearrange("p u x -> p (u x)"), reads=[F1("GY")], writes=[("gyb", tg, oc)], sem=F1("GY"))
            P.dma("sync", S["gsb"][tg, oc], sb_["GS"][:].rearrange("p u x -> p (u x)"), reads=[F1("GS")], writes=[("gsb", tg, oc)], sem=F1("GS"))
            yield

        NT = len(RW_ORDER1)
        NJ = NT * 8
        done = {"prep": set(), "c0": set(), "c1": set(), "fin": set(), "tprep": set()}

        def stream_P():
            for ti in range(NT):
                yield ("tprep", ti, lambda ti=ti: (ti == 0 or ("prep", (ti - 1) * 8 + 7) in donef), lambda ti=ti: tprep(ti))
                for oc in range(8):
                    k = ti * 8 + oc
                    yield ("prep", k, lambda k=k: ((k < 2 or (("c0", k - 2) in donef and ("c1", k - 2) in donef)) and (k < 3 or ("fin", k - 3) in donef)),
                           lambda ti=ti, oc=oc, k=k: prep(ti, oc, k % 2, k % 3))

        def stream_C(d):
            for k in range(NJ):
                ti, oc = divmod(k, 8)
                yield (f"c{d}", k, lambda k=k: (("prep", k) in donef and (k < 2 or ("fin", k - 2) in donef)),
                       lambda ti=ti, oc=oc, k=k: chain(ti, oc, k % 2, d, k % 3))

        def stream_F():
            for k in range(NJ):
                ti, oc = divmod(k, 8)
                yield ("fin", k, lambda k=k: (("c0", k) in donef and ("c1", k) in donef),
                       lambda ti=ti, oc=oc, k=k: finish(ti, oc, k % 2, k % 3))

        donef = set()
        load_hh(0)
        streams = [stream_C(0), stream_C(1), stream_F(), stream_P()]
        cur = [None] * 4
        pend = [None] * 4
        alive = [True] * 4
        while any(alive):
            progressed = False
            for si in range(4):
                if not alive[si]:
                    continue
                if cur[si] is None:
                    if pend[si] is None:
                        try:
                            pend[si] = next(streams[si])
                        except StopIteration:
                            alive[si] = False
                            continue
                    kind, k, ready, mk = pend[si]
                    if not ready():
                        continue
                    cur[si] = (kind, k, mk())
                    pend[si] = None
                kind, k, gen = cur[si]
                try:
                    next(gen)
                    progressed = True
                except StopIteration:
                    donef.add((kind, k))
                    cur[si] = None
                    progressed = True
            assert progressed or not any(alive), "scheduler stuck"


def stage_rwkv1a(P, io, G, hp, S):
    vec, masks, identb, identf, bones, onesb, rmask = (G[k] for k in ("vec", "masks", "identb", "identf", "bones", "onesb", "rmask"))
    with P.phase("rwkv1a"):
        wr = P.sb([128, 8, 1024], BF16)
        wk = P.sb([128, 8, 1024], BF16)
        wv = P.sb([128, 8, 1024], BF16)
        for w, nm in ((wr, "rwkv_wr"), (wk, "rwkv_wk"), (wv, "rwkv_wv")):
            P.dma("gpsimd", w[:], fm(io[nm]), writes=[nm], sem=nm)
        lw1 = P.sb([128, 8, 128], BF16)
        la1 = P.sb([128, 8, 128], BF16)
        g1 = P.sb([128, 8, 128], BF16)
        for d in range(2):
            P.dma("gpsimd", lw1[:, :, d * 64:(d + 1) * 64], io["rwkv_w1"][d].rearrange("(c p) j -> p c j", p=128), writes=["lw1"], sem=f"lw1{d}")
            P.dma("gpsimd", la1[:, :, d * 64:(d + 1) * 64], io["rwkv_a1"][d].rearrange("(c p) j -> p c j", p=128), writes=["la1"], sem=f"la1{d}")
        P.dma("gpsimd", g1[:], io["rwkv_g1"].rearrange("(c p) j -> p c j", p=128), writes=["g1"], sem="g1")
        w2s = P.sb([128, 1024], BF16)
        a2s = P.sb([128, 1024], BF16)
        g2 = P.sb([128, 1024], BF16)
        P.dma("gpsimd", w2s[:], io["rwkv_w2"].rearrange("d j f -> (d j) f"), writes=["w2s"], sem="w2s")
        P.dma("gpsimd", a2s[:], io["rwkv_a2"].rearrange("d j f -> (d j) f"), writes=["a2s"], sem="a2s")
        P.dma("gpsimd", g2[:], io["rwkv_g2"], writes=["g2"], sem="g2")

        hh = P.sb([128, 8, 384], F32)
        xx = P.sb([128, 8, 256], F32)
        xr = P.sb([128, 8, 256], BF16)
        xk = P.sb([128, 8, 256], BF16)
        xv = P.sb([128, 8, 256], BF16)
        xrot = P.sb([128, 8, 256], BF16)
        lwt = P.sb([128, 256], BF16)
        lat = P.sb([128, 256], BF16)
        sg = P.sb([128, 256], BF16)
        NSET = 3
        bufs = []
        for w_ in range(NSET):
            B_ = {"t": {}}
            for nm in ("r", "k", "sw0", "sw1", "ag0", "ag1", "kq", "lnv", "rs", "kkn", "fac", "kd0", "kd1", "b0", "b1",
                       "L", "Lx", "Lb", "E1", "E2", "E3", "ks"):
                B_["t"][nm] = P.sb([128, 256], F32, f"t{w_}_" + nm)
            B_["sqb"] = P.sb([128, 256], BF16)
            B_["RK"] = P.sb([128, 4, 2, 64], BF16)
            B_["VTbd"] = P.sb([128, 4, 128], F32)
            B_["GTbd"] = P.sb([128, 4, 128], F32)
            B_["Vf"] = P.sb([128, 4, 64], F32)
            B_["Gf"] = P.sb([128, 4, 64], F32)
            B_["ops"] = P.sb([128, 2, 4, 256], BF16)
            B_["vb"] = P.sb([128, 4, 64], BF16)
            B_["gam"] = P.sb([128, 2, 4], F32)
            bufs.append(B_)
            P.memset("gpsimd", B_["RK"][:], 0.0, [("RK", w_)])
            P.memset("gpsimd", B_["VTbd"][:], 0.0, [("VTbd", w_)])
            P.memset("gpsimd", B_["GTbd"][:], 0.0, [("GTbd", w_)])
        P.ns_set = frozenset(["r", "k", "sw0", "sw1", "ag0", "ag1", "kq", "lnv", "rs", "kkn", "fac", "kd0", "kd1", "b0", "b1",
                              "L", "Lx", "Lb", "E1", "E2", "E3", "ks", "sqb", "RK", "VTbd", "GTbd", "Vf", "Gf", "ops_st", "vb_st", "gam_st"])
        gamb_t = P.sb([128, 8, 4], F32)
        bon_t = P.sb([128, 8, 4], F32)
        ppt = [P.ps([128, 512], F32) for _ in range(4)]
        pp = [t_[:, 0:256] for t_ in ppt]
        pb = [P.ps([128, 512], F32) for _ in range(4)]
        cnt = {"pp": 0, "pb": 0}
        nmod = {"pp": 4, "pb": 4}

        def nxt(kind):
            i = cnt[kind] % nmod[kind]
            cnt[kind] += 1
            return i

        def v3(ap):
            return ap.rearrange("p (u s) -> p u s", s=64)

        def u128(ap):
            return ap.rearrange("p (u x) -> p u x", x=128)

        def load_hh(ti):
            isctx, idx = RW_ORDER1[ti]
            off = 4288 if isctx else 64 + 256 * idx
            P.dma("sync", hh[:], fm(hp[:, off - 64: off + 320]), writes=["hh"], sem="hh")

        def proj8(w_cols_fn, xb, bn, extra_r):
            i = nxt("pp")
            for c in range(8):
                P.mm(pp[i], w_cols_fn(c), xb[:, c, :], c == 0, c == 7, [(bn, c)] + extra_r, [f"pp{i}"])
            return i

        def tprep(ti):
            isctx, idx = RW_ORDER1[ti]
            hc = hh[:, :, 64:320]
            XXW = [("xx", c) for c in range(8)]
            if not isctx:
                h4 = hh[:, :, 64:320].rearrange("p c (r w) -> p c r w", w=64)
                x4 = xx[:].rearrange("p c (r w) -> p c r w", w=64)
                P.tt("vector", x4[:, 0:2, :, 1:64], h4[:, 0:2, :, 0:63], h4[:, 0:2, :, 1:64], ALU.subtract, ["hh"], XXW[0:2])
                P.ts("gpsimd", x4[:, 0:2, :, 0:1], h4[:, 0:2, :, 0:1], -1.0, 0.0, ALU.mult, ALU.add, ["hh"], [("xxe", 0)])
                P.tt("vector", x4[:, 2:4, :, 0:63], h4[:, 2:4, :, 1:64], h4[:, 2:4, :, 0:63], ALU.subtract, ["hh"], XXW[2:4])
                P.ts("gpsimd", x4[:, 2:4, :, 63:64], h4[:, 2:4, :, 63:64], -1.0, 0.0, ALU.mult, ALU.add, ["hh"], [("xxe", 1)])
                P.tt("gpsimd", xx[:, 4:6, :], hh[:, 4:6, 0:256], hh[:, 4:6, 64:320], ALU.subtract, ["hh"], XXW[4:6])
                P.tt("gpsimd", xx[:, 6:8, :], hh[:, 6:8, 128:384], hh[:, 6:8, 64:320], ALU.subtract, ["hh"], XXW[6:8])
            else:
                P.tt("vector", xx[:, 0:4, :], hh[:, 0:4, 63:319], hh[:, 0:4, 64:320], ALU.subtract, ["hh"], XXW[0:4] + [("xxe", 0)])
                P.tt("gpsimd", xx[:, 4:8, :], hh[:, 4:8, 65:321], hh[:, 4:8, 64:320], ALU.subtract, ["hh"], XXW[4:8] + [("xxe", 1)])
            yield

            def mk_xj(j, buf, bn):
                for c in range(8):
                    P.stt(buf[:, c, :], xx[:, c, :], vec[:, 5 + j, c:c + 1], hc[:, c, :], ALU.mult, ALU.add,
                          [("xx", c), ("xxe", 0), ("xxe", 1), "hh", "vec"], [(bn, c)])

            mk_xj(1, xrot, "xrot")
            yield
            i = proj8(lambda c: lw1[:, c, :], xrot, "xrot", ["lw1"])
            P.act(lwt[:], pp[i], AF.Tanh, [f"pp{i}"], ["lwt"])
            yield
            mk_xj(4, xrot, "xrot")
            yield
            i = proj8(lambda c: la1[:, c, :], xrot, "xrot", ["la1"])
            P.cp("scalar", lat[:], pp[i], [f"pp{i}"], ["lat"])
            yield
            mk_xj(5, xrot, "xrot")
            yield
            i = proj8(lambda c: g1[:, c, :], xrot, "xrot", ["g1"])
            P.act(sg[:], pp[i], AF.Sigmoid, [f"pp{i}"], ["sg"])
            yield
            mk_xj(0, xr, "xr")
            yield
            mk_xj(2, xk, "xk")
            yield
            mk_xj(3, xv, "xv")
            if ti + 1 < len(RW_ORDER1):
                load_hh(ti + 1)
            yield

        def prep(ti, oc, w):
            isctx, idx = RW_ORDER1[ti]
            tg = 16 if isctx else idx
            cs = slice(oc * 128, (oc + 1) * 128)
            B_ = bufs[w]
            t, sqb, RK, VTbd, GTbd, Vf, Gf = B_["t"], B_["sqb"], B_["RK"], B_["VTbd"], B_["GTbd"], B_["Vf"], B_["Gf"]
            ops_st, vb_st, gam_st = B_["ops"], B_["vb"], B_["gam"]
            i = proj8(lambda c: wr[:, c, cs], xr, "xr", ["rwkv_wr"])
            P.cp("scalar", t["r"][:], pp[i], [f"pp{i}"], ["r"])
            i = proj8(lambda c: wk[:, c, cs], xk, "xk", ["rwkv_wk"])
            P.cp("scalar", t["k"][:], pp[i], [f"pp{i}"], ["k"])
            i = proj8(lambda c: wv[:, c, cs], xv, "xv", ["rwkv_wv"])
            vt4 = VTbd[:].rearrange("p u (h s) -> p u h s", h=2)
            for h2 in range(2):
                sl = slice(h2 * 64, (h2 + 1) * 64)
                P.cp("scalar", vt4[sl, :, h2, :], v3(pp[i][sl, :]), [f"pp{i}"], ["VTbd"])
            i = nxt("pp")
            P.mm(pp[i], g2[:, cs], sg[:], True, True, ["g2", "sg"], [f"pp{i}"])
            gt4 = GTbd[:].rearrange("p u (h s) -> p u h s", h=2)
            for h2 in range(2):
                sl = slice(h2 * 64, (h2 + 1) * 64)
                P.cp("scalar", gt4[sl, :, h2, :], v3(pp[i][sl, :]), [f"pp{i}"], ["GTbd"])
            yield
            j = nxt("pb")
            for u in range(4):
                P.tr(pb[j][:, u * 128:(u + 1) * 128], VTbd[:, u, :], identf[:], ["VTbd", "identf"], [f"pb{j}"])
            pv = u128(pb[j][:])
            for h2 in range(2):
                sl = slice(h2 * 64, (h2 + 1) * 64)
                P.cp("scalar", Vf[sl, :, :], pv[sl, :, h2 * 64:(h2 + 1) * 64], [f"pb{j}"], ["Vf"])
            P.cp("gpsimd", vb_st[:], Vf[:], ["Vf"], ["vb_st"])
            P.dma("sync", S["vst"][tg, oc].rearrange("p (u s) -> p u s", s=64), Vf[:], reads=["Vf"], writes=[("vst", tg, oc)], sem="Vf")
            j = nxt("pb")
            for u in range(4):
                P.tr(pb[j][:, u * 128:(u + 1) * 128], GTbd[:, u, :], identf[:], ["GTbd", "identf"], [f"pb{j}"])
            pv = u128(pb[j][:])
            for h2 in range(2):
                sl = slice(h2 * 64, (h2 + 1) * 64)
                P.cp("scalar", Gf[sl, :, :], pv[sl, :, h2 * 64:(h2 + 1) * 64], [f"pb{j}"], ["Gf"])
            P.dma("sync", S["gst"][tg, oc].rearrange("p (u s) -> p u s", s=64), Gf[:], reads=["Gf"], writes=[("gst", tg, oc)], sem="Gf")
            yield
            for d in range(2):
                dl = slice(d * 64, (d + 1) * 64)
                i = nxt("pp")
                P.mm(pp[i], w2s[dl, cs], lwt[dl, :], True, True, ["w2s", "lwt"], [f"pp{i}"])
                P.act(t[f"sw{d}"][:], pp[i], AF.Sigmoid, [f"pp{i}", "vec"], [f"sw{d}"], bias=vec[:, 11 + d, oc:oc + 1])
                i = nxt("pp")
                P.mm(pp[i], a2s[dl, cs], lat[dl, :], True, True, ["a2s", "lat"], [f"pp{i}"])
                P.act(t[f"ag{d}"][:], pp[i], AF.Sigmoid, [f"pp{i}", "vec"], [f"ag{d}"], bias=vec[:, 13 + d, oc:oc + 1])
            yield
            P.ts("vector", t["kq"][:], t["k"][:], vec[:, 15, oc:oc + 1], None, ALU.mult, None, ["k", "vec"], ["kq"])
            P.act(sqb[:], t["kq"][:], AF.Square, ["kq"], ["sqb"])
            i = nxt("pp")
            P.mm(pp[i], bones[:], sqb[:], True, True, ["bones", "sqb"], [f"pp{i}"])
            P.act(t["lnv"][:], pp[i], AF.Ln, [f"pp{i}"], ["lnv"], bias=1e-12)
            P.act(t["rs"][:], t["lnv"][:], AF.Exp, ["lnv"], ["rs"], scale=-0.5)
            P.tt("vector", t["kkn"][:], t["kq"][:], t["rs"][:], ALU.mult, ["kq", "rs"], ["kkn"])
            for d in range(2):
                sw, ag, kd, bb = t[f"sw{d}"], t[f"ag{d}"], t[f"kd{d}"], t[f"b{d}"]
                EE = "vector"
                P.ts(EE, t["fac"][:], ag[:], vec[:, 16, oc:oc + 1], vec[:, 17, oc:oc + 1], ALU.mult, ALU.add, [f"ag{d}", "vec"], ["fac"])
                P.tt(EE, kd[:], t["k"][:], t["fac"][:], ALU.mult, ["k", "fac"], [f"kd{d}"])
                P.tt(EE, bb[:], t["kkn"][:], ag[:], ALU.mult, ["kkn", f"ag{d}"], [f"b{d}"])
                P.op("vector", lambda e, sw=sw: e.tensor_tensor_scan(out=t["L"][:], data0=rmask[:], data1=sw[:], initial=0.0,
                                                                      op0=ALU.mult, op1=ALU.add), [f"sw{d}", "rmask"], ["L"])
                L3 = v3(t["L"][:])
                if d == 0:
                    P.tt(EE, t["Lx"][:], t["L"][:], sw[:], ALU.subtract, ["L", f"sw{d}"], ["Lx"])
                    Li, Lin = t["L"], "L"
                else:
                    P.tt(EE, v3(t["Lx"][:]), L3[:, :, 63:64].broadcast_to([128, 4, 64]), L3, ALU.subtract, ["L"], ["Lx"])
                    P.tt(EE, t["Lb"][:], t["Lx"][:], sw[:], ALU.add, ["Lx", f"sw{d}"], ["Lb"])
                    Li, Lin = t["Lb"], "Lb"
                P.act(t["E1"][:], Li[:], AF.Exp, [Lin], ["E1"], scale=-C0)
                P.act(t["E3"][:], Li[:], AF.Exp, [Lin], ["E3"], scale=C0)
                P.act(t["E2"][:], t["Lx"][:], AF.Exp, ["Lx"], ["E2"], scale=-C0)
                P.stt(ops_st[:, d, 0, :], t["kkn"][:], -1.0, t["E2"][:], ALU.mult, ALU.mult, ["kkn", "E2"], ["ops_st"])
                P.tt("vector", ops_st[:, d, 1, :], t["r"][:], t["E1"][:], ALU.mult, ["r", "E1"], ["ops_st"])
                P.tt(EE, ops_st[:, d, 2, :], kd[:], t["E3"][:], ALU.mult, [f"kd{d}", "E3"], ["ops_st"])
                P.tt(EE, ops_st[:, d, 3, :], bb[:], t["E3"][:], ALU.mult, [f"b{d}", "E3"], ["ops_st"])
                E13 = v3(t["E1"][:])
                gsrc = E13[:, :, 63] if d == 0 else E13[:, :, 0]
                P.cp("vector", gam_st[:, d, :], gsrc, ["E1"], ["gam_st"])
                if d == 1:
                    P.cp("gpsimd", gamb_t[:, oc, :], gam_st[:, 1, :], ["gam_st"], ["gamb_t"])
                yield
            P.tt("vector", t["ks"][:], t["kd0"][:], t["kd1"][:], ALU.add, ["kd0", "kd1"], ["ks"])
            for h2 in range(2):
                sl = slice(h2 * 64, (h2 + 1) * 64)
                P.stt(RK[sl, :, h2, :], v3(t["r"][sl, :]), vec[sl, 18, oc:oc + 1], v3(t["ks"][sl, :]), ALU.mult, ALU.mult, ["r", "ks", "vec"], ["RK"])
            i = nxt("pp")
            for u in range(4):
                P.mm(pp[i][:, u:u + 1], RK[:, u, :, :].rearrange("p h s -> p (h s)"), onesb[:, 0:1], True, True, ["RK", "onesb"], [f"pp{i}"])
            P.cp("scalar", bon_t[:, oc, :], pp[i][:, 0:4], [f"pp{i}"], ["bon_t"])
            P.dma("sync", S["ops"][tg, oc], ops_st[:].rearrange("p d x n -> p (d x n)"), reads=["ops_st"], writes=[("ops", tg, oc)], sem="ops_st")
            P.dma("sync", S["vb"][tg, oc], vb_st[:].rearrange("p u s -> p (u s)"), reads=["vb_st"], writes=[("vb", tg, oc)], sem="vb_st")
            P.dma("sync", S["gam"][tg, oc], gam_st[:].rearrange("p d u -> p (d u)"), reads=["gam_st"], writes=[("gam", tg, oc)], sem="gam_st")
            yield


        NT = len(RW_ORDER1)
        load_hh(0)
        for ti in range(NT):
            isctx, idx = RW_ORDER1[ti]
            tg = 16 if isctx else idx
            for _ in tprep(ti):
                pass
            jobs = [(oc % NSET, prep(ti, oc, oc % NSET)) for oc in range(8)]
            active = []
            since = 99
            while jobs or active:
                if jobs and len(active) < NSET and (since >= 3 or not active):
                    active.append(jobs.pop(0))
                    since = 0
                since += 1
                for item in list(active):
                    P.ns = item[0]
                    try:
                        next(item[1])
                    except StopIteration:
                        active.remove(item)
                    P.ns = None
            P.dma("sync", S["gamb"][tg], gamb_t[:].rearrange("p a b -> p (a b)"), reads=["gamb_t"], writes=[("gamb", tg)], sem="gamb_t")
            P.dma("sync", S["bon"][tg], bon_t[:].rearrange("p a b -> p (a b)"), reads=["bon_t"], writes=[("bon", tg)], sem="bon_t")
        P.ns_set = frozenset()


def stage_rwkv1b(P, io, G, S):
    vec, masks, identb, identf, bones, onesb, rmask = (G[k] for k in ("vec", "masks", "identb", "identf", "bones", "onesb", "rmask"))
    with P.phase("rwkv1b"):
        YPs = P.sb([128, 4, 64], F32)
        SAs = P.sb([128, 4, 64], F32)
        Sf = P.sb([128, 8, 64], BF16)
        ARq = [[P.sb([128, 4, 2, 128], BF16, f"AR{q}{d}") for d in range(2)] for q in range(3)]
        KTq = [[P.sb([128, 4, 128], BF16, f"KT{q}{d}") for d in range(2)] for q in range(3)]
        BTq = [[P.sb([128, 4, 128], BF16, f"BT{q}{d}") for d in range(2)] for q in range(3)]
        stg = [P.sb([128, 2, 4, 256], BF16, f"stg{q}") for q in range(3)]
        Vbq = [P.sb([128, 4, 64], BF16, f"Vb{q}") for q in range(4)]
        gamq = [P.sb([128, 2, 4], F32, f"gam{q}") for q in range(4)]
        inv2 = []
        for q in range(2):
            row = []
            for d in range(2):
                st = {}
                for nm, shp in (("Atok", [128, 4, 128]), ("Btok", [128, 4, 128]), ("MQ", [128, 4, 256]), ("MWa", [128, 4, 2, 128]),
                                ("MWb", [128, 4, 2, 128]), ("MTa", [128, 4, 128]), ("MTb", [128, 4, 128])):
                    st[nm] = P.sb(shp, BF16, f"i{q}{d}_{nm}")
                row.append(st)
            inv2.append(row)
        fin = []
        for q in range(2):
            row = []
            for d in range(2):
                st = {}
                for nm, shp in (("Ktok", [128, 4, 128]), ("NP", [128, 4, 256]), ("XW", [128, 4, 256]), ("NVb", [128, 4, 64]),
                                ("GY", [128, 4, 128]), ("GS", [128, 4, 128])):
                    st[nm] = P.sb(shp, BF16, f"f{q}{d}_{nm}")
                row.append(st)
            fin.append(row)
        pf = P.ps([128, 512], F32)
        pb = [P.ps([128, 512], F32) for _ in range(7)]
        cnt = {"pb": 0}
        nmod = {"pb": 7}

        def nxt(kind):
            i = cnt[kind] % nmod[kind]
            cnt[kind] += 1
            return i

        for q in range(3):
            for d in range(2):
                P.memset("gpsimd", ARq[q][d][:], 0.0, [f"AR{q}{d}"])
                P.memset("gpsimd", KTq[q][d][:], 0.0, [f"KT{q}{d}"])
                P.memset("gpsimd", BTq[q][d][:], 0.0, [f"BT{q}{d}"])
        P.memset("gpsimd", Sf[:], 0.0, [("Sf", p) for p in range(8)])

        def v3(ap):
            return ap.rearrange("p (u s) -> p u s", s=64)

        def u128(ap):
            return ap.rearrange("p (u x) -> p u x", x=128)

        def loadjob(ti, oc, a, z):
            isctx, idx = RW_ORDER1[ti]
            tg = 16 if isctx else idx
            sg_ = stg[a]
            P.dma("sync", sg_[:].rearrange("p d x n -> p (d x n)"), S["ops"][tg, oc], writes=[f"stg{a}"], sem=f"stg{a}")
            P.dma("sync", Vbq[z][:].rearrange("p u s -> p (u s)"), S["vb"][tg, oc], writes=[f"Vb{z}"], sem=f"Vb{z}")
            P.dma("sync", gamq[z][:].rearrange("p d u -> p (d u)"), S["gam"][tg, oc], writes=[f"gam{z}"], sem=f"gam{z}")
            yield
            for d in range(2):
                ar5 = ARq[a][d][:].rearrange("p u a (h s) -> p u a h s", h=2)
                kt4 = KTq[a][d][:].rearrange("p u (h s) -> p u h s", h=2)
                bt4 = BTq[a][d][:].rearrange("p u (h s) -> p u h s", h=2)
                for h2 in range(2):
                    sl = slice(h2 * 64, (h2 + 1) * 64)
                    P.cp("gpsimd", ar5[sl, :, 0, h2, :], v3(sg_[sl, d, 0, :]), [f"stg{a}"], [f"AR{a}{d}"])
                    P.cp("gpsimd", ar5[sl, :, 1, h2, :], v3(sg_[sl, d, 1, :]), [f"stg{a}"], [f"AR{a}{d}"])
                    P.cp("gpsimd", kt4[sl, :, h2, :], v3(sg_[sl, d, 2, :]), [f"stg{a}"], [f"KT{a}{d}"])
                    P.cp("gpsimd", bt4[sl, :, h2, :], v3(sg_[sl, d, 3, :]), [f"stg{a}"], [f"BT{a}{d}"])
                    yield

        def chain(ti, oc, q, d, z, a):
            AR, KT, BT, Vb = ARq[a][d], KTq[a][d], BTq[a][d], Vbq[z]
            ARn, KTn, BTn, Vbn = f"AR{a}{d}", f"KT{a}{d}", f"BT{a}{d}", f"Vb{z}"
            iv, fn = inv2[q][d], fin[q][d]
            IR = lambda nm: f"i{q}{d}_{nm}"
            FR = lambda nm: f"f{q}{d}_{nm}"
            mS, mC = (0, 2) if d == 0 else (2, 0)
            mSI = masks[:, mS:mS + 2, :].rearrange("p a b -> p (a b)").unsqueeze(1).broadcast_to([128, 4, 256])
            mCb = masks[:, mC, :].unsqueeze(1).broadcast_to([128, 4, 128])
            idb = identb[:].unsqueeze(1).broadcast_to([128, 4, 128])
            for src, srcn, dst, dstn in ((AR[:, :, 0, :], ARn, iv["Atok"], IR("Atok")), (BT[:], BTn, iv["Btok"], IR("Btok")),
                                         (KT[:], KTn, fn["Ktok"], FR("Ktok"))):
                j = nxt("pb")
                pbt = pb[j][:].bitcast(BF16)
                for u in range(4):
                    P.tr(pbt[:, u * 128:(u + 1) * 128], src[:, u, :], identb[:], [srcn, "identb"], [f"pb{j}"])
                P.cp("scalar", dst[:].rearrange("p u x -> p (u x)"), pbt[:, 0:512], [f"pb{j}"], [dstn])
            mSb = masks[:, mS, :].unsqueeze(1).broadcast_to([128, 4, 128])
            mIb = masks[:, mS + 1, :].unsqueeze(1).broadcast_to([128, 4, 128])

            def two_bank(mm_fn):
                j0, j1 = nxt("pb"), nxt("pb")
                for u in range(4):
                    mm_fn(u, pb[j0][:, u * 128:(u + 1) * 128], f"pb{j0}", pb[j1][:, u * 128:(u + 1) * 128], f"pb{j1}")
                return j0, j1

            for lhs, lhsn, dst, dstn in ((BT, BTn, iv["MQ"], IR("MQ")), (KT, KTn, fn["NP"], FR("NP"))):
                def mm_ab(u, o0, n0, o1, n1, lhs=lhs, lhsn=lhsn):
                    P.mm(o0, lhs[:, u, :], AR[:, u, 0, :], True, True, [lhsn, ARn], [n0])
                    P.mm(o1, lhs[:, u, :], AR[:, u, 1, :], True, True, [lhsn, ARn], [n1])
                j0, j1 = two_bank(mm_ab)
                P.tt("vector", dst[:, :, 0:128], u128(pb[j0][:]), mSb, ALU.mult, [f"pb{j0}", "masks"], [dstn])
                P.tt("vector", dst[:, :, 128:256], u128(pb[j1][:]), mIb, ALU.mult, [f"pb{j1}", "masks"], [dstn])
            j = nxt("pb")
            for u in range(4):
                P.mm(pb[j][:, u * 128:(u + 1) * 128], AR[:, u, 0, :], BT[:, u, :], True, True, [ARn, BTn], [f"pb{j}"])
            cur, curn, nx, nxn = iv["MWa"], IR("MWa"), iv["MWb"], IR("MWb")
            P.tt("vector", cur[:, :, 0, :], u128(pb[j][:]), mCb, ALU.mult, [f"pb{j}", "masks"], [curn])
            yield
            j = nxt("pb")
            for u in range(4):
                P.mm(pb[j][:, u * 128:(u + 1) * 128], iv["MQ"][:, u, 0:128], cur[:, u, 0, :], True, True, [IR("MQ"), curn], [f"pb{j}"])
            P.cp("scalar", nx[:, :, 0, :], u128(pb[j][:]), [f"pb{j}"], [nxn])
            P.tt("gpsimd", nx[:, :, 1, :], cur[:, :, 0, :], idb, ALU.add, [curn, "identb"], [nxn])
            j = nxt("pb")
            for u in range(4):
                P.mm(pb[j][:, u * 128:(u + 1) * 128], cur[:, u, 0, :], iv["MQ"][:, u, 0:128], True, True, [IR("MQ"), curn], [f"pb{j}"])
            curT, curTn, nxT, nxTn = iv["MTa"], IR("MTa"), iv["MTb"], IR("MTb")
            P.cp("scalar", curT[:], u128(pb[j][:]), [f"pb{j}"], [curTn])
            cur, curn, nx, nxn = nx, nxn, cur, curn
            yield
            for lev in range(1, 5):
                def mm_lev(u, o0, n0, o1, n1, cur=cur, curn=curn, curT=curT, curTn=curTn):
                    P.mm(o0, curT[:, u, :], cur[:, u, 0, :], True, True, [curTn, curn], [n0])
                    P.mm(o1, curT[:, u, :], cur[:, u, 1, :], True, True, [curTn, curn], [n1])
                j0, j1 = two_bank(mm_lev)
                P.cp("scalar", nx[:, :, 0, :], u128(pb[j0][:]), [f"pb{j0}"], [nxn])
                P.tt("vector", nx[:, :, 1, :], u128(pb[j1][:]), cur[:, :, 1, :], ALU.add, [f"pb{j1}", curn], [nxn])
                j = nxt("pb")
                for u in range(4):
                    P.mm(pb[j][:, u * 128:(u + 1) * 128], cur[:, u, 0, :], curT[:, u, :], True, True, [curn, curTn], [f"pb{j}"])
                P.cp("scalar", nxT[:], u128(pb[j][:]), [f"pb{j}"], [nxTn])
                cur, curn, nx, nxn = nx, nxn, cur, curn
                curT, curTn, nxT, nxTn = nxT, nxTn, curT, curTn
                yield
            j = nxt("pb")
            for u in range(4):
                P.mm(pb[j][:, u * 128:(u + 1) * 128], curT[:, u, :], cur[:, u, 1, :], True, True, [curTn, curn], [f"pb{j}"])
            P.tt("vector", nx[:, :, 1, :], u128(pb[j][:]), cur[:, :, 1, :], ALU.add, [f"pb{j}", curn], [nxn])
            W6, W6n = nx, nxn
            j = nxt("pb")
            for u in range(4):
                P.mm(pb[j][:, u * 64:(u + 1) * 64], fn["NP"][:, u, 0:128], Vb[:, u, :], True, True, [FR("NP"), Vbn], [f"pb{j}"])
            P.cp("scalar", fn["NVb"][:].rearrange("p u x -> p (u x)"), pb[j][:, 0:256], [f"pb{j}"], [FR("NVb")])
            yield

            def mm_d(u, o0, n0, o1, n1):
                P.mm(o0, W6[:, u, 1, :], iv["MQ"][:, u, 128:256], True, True, [W6n, IR("MQ")], [n0])
                P.mm(o1, W6[:, u, 1, :], iv["Btok"][:, u, :], True, True, [W6n, IR("Btok")], [n1])
            j0, j1 = two_bank(mm_d)
            P.cp("scalar", fn["XW"][:, :, 0:128], u128(pb[j0][:]), [f"pb{j0}"], [FR("XW")])
            P.cp("vector", fn["XW"][:, :, 128:256], u128(pb[j1][:]), [f"pb{j1}"], [FR("XW")])
            yield

            def mm_f(u, o0, n0, o1, n1):
                P.mm(o0, iv["Atok"][:, u, :], fn["XW"][:, u, 0:128], True, True, [IR("Atok"), FR("XW")], [n0])
                P.mm(o1, iv["Atok"][:, u, :], fn["XW"][:, u, 128:256], True, True, [IR("Atok"), FR("XW")], [n1])
            j0, j1 = two_bank(mm_f)
            P.tt("vector", fn["GY"][:], u128(pb[j0][:]), AR[:, :, 1, :], ALU.add, [f"pb{j0}", ARn], [FR("GY")])
            P.tt("vector", fn["GS"][:], u128(pb[j1][:]), idb, ALU.add, [f"pb{j1}", "identb"], [FR("GS")])
            yield

        def finish(ti, oc, q, z):
            isctx, idx = RW_ORDER1[ti]
            tg = 16 if isctx else idx
            sf, sb_ = fin[q]
            F0 = lambda nm: f"f{q}0_{nm}"
            F1 = lambda nm: f"f{q}1_{nm}"
            Vb, Vbn, gamz = Vbq[z], f"Vb{z}", gamq[z]
            SFR = ("Sf", oc)
            for u in range(4):
                yo = pf[:, u * 64:(u + 1) * 64]
                P.mm(yo, sf["NP"][:, u, 128:256], Vb[:, u, :], True, False, [F0("NP"), Vbn], ["pf"])
                P.mm(yo, sf["XW"][:, u, 0:128], sf["NVb"][:, u, :], False, False, [F0("XW"), F0("NVb")], ["pf"])
                P.mm(yo, sb_["NP"][:, u, 128:256], Vb[:, u, :], False, False, [F1("NP"), Vbn], ["pf"])
                P.mm(yo, sb_["XW"][:, u, 0:128], sb_["NVb"][:, u, :], False, False, [F1("XW"), F1("NVb")], ["pf"])
                P.mm(yo, sf["GY"][:, u, :], Sf[:, oc, :], False, True, [F0("GY"), SFR], ["pf"])
                so = pf[:, 256:320]
                P.mm(so, sf["Ktok"][:, u, :], Vb[:, u, :], True, False, [F0("Ktok"), Vbn], ["pf"])
                P.mm(so, sf["XW"][:, u, 128:256], sf["NVb"][:, u, :], False, False, [F0("XW"), F0("NVb")], ["pf"])
                P.mm(so, sf["GS"][:, u, :], Sf[:, oc, :], False, True, [F0("GS"), SFR], ["pf"])
                P.ts("vector", Sf[:, oc, :], so, gamz[:, 0, u:u + 1], None, ALU.mult, None, ["pf", f"gam{z}"], [SFR])
                yield
            P.cp("vector", YPs[:].rearrange("p u x -> p (u x)"), pf[:, 0:256], ["pf"], ["YPs"])
            P.dma("sync", S["yp"][tg, oc], YPs[:].rearrange("p u x -> p (u x)"), reads=["YPs"], writes=[("yp", tg, oc)], sem="YPs")
            j = nxt("pb")
            for u in range(4):
                so = pb[j][:, u * 64:(u + 1) * 64]
                P.mm(so, sb_["Ktok"][:, u, :], Vb[:, u, :], True, False, [F1("Ktok"), Vbn], [f"pb{j}"])
                P.mm(so, sb_["XW"][:, u, 128:256], sb_["NVb"][:, u, :], False, True, [F1("XW"), F1("NVb")], [f"pb{j}"])
            P.cp("scalar", SAs[:].rearrange("p u x -> p (u x)"), pb[j][:, 0:256], [f"pb{j}"], ["SAs"])
            P.dma("sync", S["sadd"][tg, oc], SAs[:].rearrange("p u x -> p (u x)"), reads=["SAs"], writes=[("sadd", tg, oc)], sem="SAs")
            P.dma("sync", S["gyb"][tg, oc], sb_["GY"][:].rearrange("p u x -> p (u x)"), reads=[F1("GY")], writes=[("gyb", tg, oc)], sem=F1("GY"))
            P.dma("sync", S["gsb"][tg, oc], sb_["GS"][:].rearrange("p u x -> p (u x)"), reads=[F1("GS")], writes=[("gsb", tg, oc)], sem=F1("GS"))
            yield


        NT = len(RW_ORDER1)
        NJ = NT * 8
        donef = set()

        def stream_L():
            for k in range(NJ):
                ti, oc = divmod(k, 8)
                yield ("load", k, lambda k=k: ((k < 3 or (("c0", k - 3) in donef and ("c1", k - 3) in donef)) and (k < 4 or ("fin", k - 4) in donef)),
                       lambda ti=ti, oc=oc, k=k: loadjob(ti, oc, k % 3, k % 4))

        def stream_C(d, par):
            for k in range(par, NJ, 2):
                ti, oc = divmod(k, 8)
                yield (f"c{d}", k, lambda k=k: (("load", k) in donef and (k < 2 or ("fin", k - 2) in donef)),
                       lambda ti=ti, oc=oc, k=k: chain(ti, oc, k % 2, d, k % 4, k % 3))

        def stream_F():
            for k in range(NJ):
                ti, oc = divmod(k, 8)
                yield ("fin", k, lambda k=k: (("c0", k) in donef and ("c1", k) in donef),
                       lambda ti=ti, oc=oc, k=k: finish(ti, oc, k % 2, k % 4))

        streams = [stream_L(), stream_C(0, 0), stream_C(1, 0), stream_C(0, 1), stream_C(1, 1), stream_F()]
        NS_ = len(streams)
        cur = [None] * NS_
        pend = [None] * NS_
        alive = [True] * NS_
        while any(alive):
            progressed = False
            for si in range(NS_):
                if not alive[si]:
                    continue
                if cur[si] is None:
                    if pend[si] is None:
                        try:
                            pend[si] = next(streams[si])
                        except StopIteration:
                            alive[si] = False
                            continue
                    kind, k, ready, mk = pend[si]
                    if not ready():
                        continue
                    cur[si] = (kind, k, mk())
                    pend[si] = None
                kind, k, gen = cur[si]
                try:
                    next(gen)
                    progressed = True
                except StopIteration:
                    donef.add((kind, k))
                    cur[si] = None
                    progressed = True
            assert progressed or not any(alive), "scheduler stuck"


def stage_rwkv2(P, io, G, S, src, xa):
    vec, identb = G["vec"], G["identb"]
    GN_EPS = 64e-5
    with P.phase("rwkv2"):
        wo = P.sb([64, 16, 1024], BF16)
        P.dma("gpsimd", wo[:], io["rwkv_wo"].rearrange("(h v) f -> v h f", v=64), writes=["wo"], sem="wo")
        lnw = P.sb([128, 8, 64], F32)
        lnb = P.sb([128, 8, 64], F32)
        P.dma("sync", lnw[:], io["lnw_st"], writes=["lnw"], sem="lnw")
        P.dma("sync", lnb[:], io["lnb_st"], writes=["lnb"], sem="lnb")
        big = {}
        for nm in ("yp", "sadd", "vst", "gst"):
            big[nm] = [P.sb([128, 8, 256], F32, f"l_{nm}{b}") for b in range(2)]
        for nm in ("gyb", "gsb"):
            big[nm] = [P.sb([128, 8, 512], BF16, f"l_{nm}{b}") for b in range(2)]
        gamb = [P.sb([128, 8, 4], F32) for _ in range(2)]
        bon = [P.sb([128, 8, 4], F32) for _ in range(2)]
        xt = [P.sb([128, 8, 256], F32) for _ in range(2)]
        Sb = P.sb([128, 8, 64], BF16)
        ysb2 = [P.sb([128, 8, 64], F32) for _ in range(2)]
        ysq2 = [P.sb([128, 8, 64], F32) for _ in range(2)]
        tmpS = P.sb([128, 8, 64], F32)
        yn2 = [P.sb([128, 8, 64], F32) for _ in range(2)]
        bv2 = [P.sb([128, 8, 64], F32) for _ in range(2)]
        ob2 = [P.sb([128, 8, 64], BF16) for _ in range(2)]
        st2 = [{nm: P.sb([128, 8], F32, f"g{k_}_" + nm) for nm in ("s1", "s2", "mean", "msq", "var", "lnv", "rstd")} for k_ in range(2)]
        OT = P.sb([64, 16, 256], BF16)
        py = [P.ps([128, 512], F32) for _ in range(2)]
        pS = P.ps([128, 512], F32)
        ptr = P.ps([128, 1024], F32)
        pw = [P.ps([128, 512], F32) for _ in range(2)]
        P.memset("gpsimd", Sb[:], 0.0, ["Sb"])

        def load(k):
            isctx, idx = RW_ORDER2[k]
            tg = 16 if isctx else idx
            b = k % 2
            for nm in ("yp", "sadd", "vst", "gst", "gyb", "gsb"):
                P.dma("sync", big[nm][b][:], S[nm][tg].rearrange("o p x -> p o x"), writes=[f"{nm}{b}"], sem=f"{nm}{b}")
            P.dma("sync", gamb[b][:].rearrange("p a b -> p (a b)"), S["gamb"][tg], writes=[f"gamb{b}"], sem=f"gamb{b}")
            P.dma("sync", bon[b][:].rearrange("p a b -> p (a b)"), S["bon"][tg], writes=[f"bon{b}"], sem=f"bon{b}")
            c0 = T if isctx else idx * 256
            P.dma("sync", xt[b][:], fm(src[:, c0:c0 + 256]), writes=[f"xt{b}"], sem=f"xt{b}")

        load(0)
        for k, (isctx, idx) in enumerate(RW_ORDER2):
            b = k % 2
            if k + 1 < len(RW_ORDER2):
                load(k + 1)
            c0 = T if isctx else idx * 256
            _, _, gates = mod_scalars(G, 0, 0, isctx)
            bc = lambda ap: ap.unsqueeze(2).broadcast_to([128, 8, 64])
            def chain_part(u):
                us = slice(u * 64, (u + 1) * 64)
                q_ = u % 2
                for oc in range(8):
                    P.mm(py[q_][:, oc * 64:(oc + 1) * 64], big["gyb"][b][:, oc, u * 128:(u + 1) * 128], Sb[:, oc, :], True, True, [f"gyb{b}", "Sb"], [f"py{q_}"])
                for oc in range(8):
                    P.mm(pS[:, oc * 64:(oc + 1) * 64], big["gsb"][b][:, oc, u * 128:(u + 1) * 128], Sb[:, oc, :], True, True, [f"gsb{b}", "Sb"], ["pS"])
                pS3 = pS[:].rearrange("p (o v) -> p o v", v=64)
                P.tt("vector", tmpS[:], pS3, big["sadd"][b][:, :, us], ALU.add, ["pS", f"sadd{b}"], ["tmpS"])
                P.tt("vector", Sb[:], tmpS[:], bc(gamb[b][:, :, u]), ALU.mult, ["tmpS", f"gamb{b}"], ["Sb"])

            def read_part(u):
                us = slice(u * 64, (u + 1) * 64)
                q_ = u % 2
                ysb, ysq, yn, bv, ob, st = ysb2[q_], ysq2[q_], yn2[q_], bv2[q_], ob2[q_], st2[q_]
                N = lambda nm: f"{nm}{q_}"
                py3 = py[q_][:].rearrange("p (o v) -> p o v", v=64)
                P.tt("vector", ysb[:], py3, big["yp"][b][:, :, us], ALU.add, [f"py{q_}", f"yp{b}"], [N("ysb")])
                P.tt("gpsimd", bv[:], big["vst"][b][:, :, us], bc(bon[b][:, :, u]), ALU.mult, [f"vst{b}", f"bon{b}"], [N("bv")])
                yield
                P.op("vector", lambda e: e.tensor_reduce(out=st["s1"][:], in_=ysb[:], axis=AX.X, op=ALU.add), [N("ysb")], [N("s1")])
                P.tt("gpsimd", ysq[:], ysb[:], ysb[:], ALU.mult, [N("ysb")], [N("ysq")])
                yield
                P.op("vector", lambda e: e.tensor_reduce(out=st["s2"][:], in_=ysq[:], axis=AX.X, op=ALU.add), [N("ysq")], [N("s2")])
                P.ts("vector", st["mean"][:], st["s1"][:], 1.0 / 64, None, ALU.mult, None, [N("s1")], [N("mean")])
                P.tt("vector", st["msq"][:], st["mean"][:], st["mean"][:], ALU.mult, [N("mean")], [N("msq")])
                P.stt(st["var"][:], st["s2"][:], 1.0 / 64, st["msq"][:], ALU.mult, ALU.subtract, [N("s2"), N("msq")], [N("var")])
                yield
                P.act(st["lnv"][:], st["var"][:], AF.Ln, [N("var")], [N("lnv")], bias=GN_EPS)
                P.act(st["rstd"][:], st["lnv"][:], AF.Exp, [N("lnv")], [N("rstd")], scale=-0.5)
                P.tt("gpsimd", yn[:], ysb[:], bc(st["mean"][:]), ALU.subtract, [N("ysb"), N("mean")], [N("yn")])
                yield
                P.tt("vector", yn[:], yn[:], bc(st["rstd"][:]), ALU.mult, [N("yn"), N("rstd")], [N("yn")])
                yield
                P.tt("gpsimd", yn[:], yn[:], lnw[:], ALU.mult, [N("yn"), "lnw"], [N("yn")])
                yield
                P.tt("vector", yn[:], yn[:], lnb[:], ALU.add, [N("yn"), "lnb"], [N("yn")])
                yield
                P.tt("gpsimd", yn[:], yn[:], bv[:], ALU.add, [N("yn"), N("bv")], [N("yn")])
                yield
                P.tt("vector", ob[:], yn[:], big["gst"][b][:, :, us], ALU.mult, [N("yn"), f"gst{b}"], [N("ob")])
                yield
                ptb = ptr[:].bitcast(BF16)
                for oc in range(8):
                    P.tr(ptb[0:64, oc * 128:(oc + 1) * 128], ob[:, oc, :], identb[:], [N("ob"), "identb"], ["ptr"])
                P.cp("scalar", OT[:, :, us], ptb[0:64, 0:1024].rearrange("p (h t) -> p h t", t=64), ["ptr"], ["OT"])
                yield

            def chain_all():
                for u in range(3, -1, -1):
                    chain_part(u)
                    yield

            jobs = [read_part(u) for u in range(3, -1, -1)]
            cgen = chain_all()
            next(cgen)
            active = []
            started = 0
            while jobs or active:
                while jobs and len(active) < 2:
                    if started >= 1:
                        try:
                            next(cgen)
                        except StopIteration:
                            pass
                    active.append(jobs.pop(0))
                    started += 1
                for gen in list(active):
                    try:
                        next(gen)
                    except StopIteration:
                        active.remove(gen)
            for oc in range(8):
                j = oc % 2
                for h in range(16):
                    P.mm(pw[j][:, 0:256], wo[:, h, oc * 128:(oc + 1) * 128], OT[:, h, :], h == 0, h == 15, ["wo", "OT"], [f"pw{j}"])
                P.stt(xt[b][:, oc, :], pw[j][:, 0:256], gates[oc], xt[b][:, oc, :], ALU.mult, ALU.add, [f"pw{j}", f"xt{b}", "modv"], [f"xt{b}"])
            P.dma("sync", fm(xa[:, c0:c0 + 256]), xt[b][:], reads=[f"xt{b}"], writes=[("xa", k)], sem=f"xt{b}")


def stage_qkv(P, io, G, hb, qtd, Kz, VA):
    vec, bones, perm = G["vec"], G["bones"], G["perm"]
    with P.phase("qkv"):
        wq = P.sb([128, 8, 1024], BF16)
        wkd = P.sb([128, 8, 512], BF16)
        wv = P.sb([128, 8, 256], BF16)
        P.dma("gpsimd", wq[:], fm(io["attn_wq"]), writes=["wq"], sem="wq")
        P.dma("gpsimd", wkd[:], fm(io["attn_wkd"]), writes=["wkd"], sem="wkd")
        P.dma("gpsimd", wv[:], fm(io["attn_wv"]), writes=["wv"], sem="wv")
        ht = [P.sb([128, 8, 512], BF16) for _ in range(2)]
        cs = [P.sb([128, 512], F32) for _ in range(2)]
        sn = [P.sb([128, 512], F32) for _ in range(2)]
        NB = 2
        qf = [P.sb([128, 512], F32) for _ in range(NB)]
        sqb = [P.sb([128, 512], BF16) for _ in range(NB)]
        lnv = [P.sb([128, 512], F32) for _ in range(NB)]
        rstd = [P.sb([128, 512], F32) for _ in range(NB)]
        qh = [P.sb([128, 512], F32) for _ in range(NB)]
        qhb = [P.sb([128, 512], BF16) for _ in range(NB)]
        t1 = [P.sb([128, 512], F32) for _ in range(NB)]
        t2 = [P.sb([128, 512], F32) for _ in range(NB)]
        qst = [P.sb([128, 8, 512], BF16) for _ in range(2)]
        pp = [P.ps([128, 512], F32) for _ in range(6)]
        cnt = [0, 0]

        def nxt():
            cnt[0] += 1
            return cnt[0] % 6

        P.memset("gpsimd", VA[:], 0.0, ["VA0"])
        P.memset("gpsimd", VA[:].rearrange("p k (j x) -> p k j x", x=65)[:, :, 0:5, 64:65], 1.0, ["VA0"])
        P.memset("gpsimd", Kz[0][64:128, :, :], 0.0, ["Kz0z"])
        P.memset("gpsimd", Kz[1][0:64, :, :], 0.0, ["Kz1z"])
        tiles = ALL_TILES

        def load(i):
            c0, tw, isctx = tiles[i]
            b = i % 2
            P.dma("sync", ht[b][:, :, :tw], fm(hb[:, c0:c0 + tw]), writes=[f"ht{b}"], sem=f"ht{b}")
            if not isctx:
                P.dma("sync", cs[b][:, :tw], io["cosT"][:, c0:c0 + tw], writes=[f"cs{b}"], sem=f"cs{b}")
                P.dma("sync", sn[b][:, :tw], io["sinT"][:, c0:c0 + tw], writes=[f"sn{b}"], sem=f"sn{b}")

        def normrope(wcols, nscal, dsts, b, tw, isctx, wname, dres="dstqk"):
            cnt[1] += 1
            n = cnt[1] % NB
            i = nxt()
            for c in range(8):
                P.mm(pp[i][:, :tw], wcols(c), ht[b][:, c, :tw], c == 0, c == 7, [wname, f"ht{b}"], [f"pp{i}"])
            P.cp("scalar", qf[n][:, :tw], pp[i][:, :tw], [f"pp{i}"], [f"qf{n}"])
            P.act(sqb[n][:, :tw], qf[n][:, :tw], AF.Square, [f"qf{n}"], [f"sqb{n}"])
            yield
            i = nxt()
            P.mm(pp[i][:, :tw], bones[:], sqb[n][:, :tw], True, True, ["bones", f"sqb{n}"], [f"pp{i}"])
            P.act(lnv[n][:, :tw], pp[i][:, :tw], AF.Ln, [f"pp{i}"], [f"lnv{n}"], bias=1e-6, scale=1.0 / 64)
            P.act(rstd[n][:, :tw], lnv[n][:, :tw], AF.Exp, [f"lnv{n}"], [f"rstd{n}"], scale=-0.5)
            yield
            P.stt(qh[n][:, :tw], qf[n][:, :tw], nscal, rstd[n][:, :tw], ALU.mult, ALU.mult, [f"qf{n}", f"rstd{n}", "vec"], [f"qh{n}"])
            if isctx:
                for dst, sl in dsts:
                    P.cp("gpsimd", dst, qh[n][sl, :tw], [f"qh{n}"], [dres])
                return
            P.cp("gpsimd", qhb[n][:, :tw], qh[n][:, :tw], [f"qh{n}"], [f"qhb{n}"])
            yield
            i = nxt()
            P.mm(pp[i][:, :tw], perm[:], qhb[n][:, :tw], True, True, ["perm", f"qhb{n}"], [f"pp{i}"])
            P.tt("gpsimd", t1[n][:, :tw], qh[n][:, :tw], cs[b][:, :tw], ALU.mult, [f"qh{n}", f"cs{b}"], [f"t1{n}"])
            P.tt("vector", t2[n][:, :tw], pp[i][:, :tw], sn[b][:, :tw], ALU.mult, [f"pp{i}", f"sn{b}"], [f"t2{n}"])
            yield
            for dst, sl in dsts:
                P.tt("gpsimd", dst, t1[n][sl, :tw], t2[n][sl, :tw], ALU.add, [f"t1{n}", f"t2{n}"], [dres])

        ALLP = slice(0, 128)
        load(0)
        for i, (c0, tw, isctx) in enumerate(tiles):
            b = i % 2
            if i + 1 < len(tiles):
                load(i + 1)
            jobs = []
            if not isctx:
                for oc in range(8):
                    jobs.append(normrope(lambda c, oc=oc: wq[:, c, oc * 128:(oc + 1) * 128], vec[:, 19, oc:oc + 1], [(qst[b][:, oc, :tw], ALLP)], b, tw, False, "wq",
                                         dres=(f"qst{b}", oc)))
            for g in range(4):
                jobs.append(normrope(lambda c, g=g: wkd[:, c, g * 128:(g + 1) * 128], vec[:, 20, 0:1],
                                     [(Kz[0][0:64, g, c0:c0 + tw], slice(0, 64)), (Kz[1][64:128, g, c0:c0 + tw], slice(64, 128))], b, tw, isctx, "wkd"))

            def vjob():
                for sub in range(tw // 128):
                    kt = c0 // 128 + sub
                    j = nxt()
                    for c in range(8):
                        P.mm(pp[j][:, 0:256], ht[b][:, c, sub * 128:(sub + 1) * 128], wv[:, c, :], c == 0, c == 7, ["wv", f"ht{b}"], [f"pp{j}"])
                    P.cp("scalar", VA[:, kt, 65:325].rearrange("p (g x) -> p g x", x=65)[:, :, 0:64],
                         pp[j][:, 0:256].rearrange("p (g d) -> p g d", d=64), [f"pp{j}", "VA0"], [("VA", kt)])
                    yield

            jobs.append(vjob())
            active = []
            while jobs or active:
                while jobs and len(active) < 2:
                    active.append(jobs.pop(0))
                for gen in list(active):
                    try:
                        next(gen)
                    except StopIteration:
                        active.remove(gen)
            if not isctx:
                P.dma("sync", fm(qtd[:, c0:c0 + tw]), qst[b][:, :, :tw], reads=[(f"qst{b}", oc) for oc in range(8)], writes=[("qtd", i)], sem=f"qst{b}")


def stage_attn(P, io, G, qtd, Kz, VA, xa):
    with P.phase("attn"):
        wo = P.sb([128, 8, 1024], BF16)
        P.dma("gpsimd", wo[:], fm(io["attn_wo"]), writes=["wo"], sem="wo")
        sel = P.sb([128, 2, 128], F32)
        P.dma("sync", sel[:], io["c_sel"], writes=["sel"], sem="sel")
        PT = [P.sb([128, 1024], BF16) for _ in range(3)]
        osb = [P.sb([128, 512], F32) for _ in range(2)]
        rb = [P.sb([128, 512], F32) for _ in range(2)]
        xt = P.sb([128, 8, 512], F32)
        QB = [P.sb([128, 8, 512], BF16) for _ in range(2)]
        psS = [P.ps([128, 1024], F32) for _ in range(2)]
        psO = [P.ps([128, 512], F32) for _ in range(2)]
        psB = P.ps([128, 512], F32)
        pX = [P.ps([128, 512], F32) for _ in range(1)]
        _, _, gates = mod_scalars(G, 1, 0, False)
        for k in range(2):
            P.memset("gpsimd", osb[k][:], 0.0, [f"osb{k}"])
        def loadq(qb):
            P.dma("sync", QB[qb % 2][:], fm(qtd[:, qb * 512:(qb + 1) * 512]), writes=[("QT", h, qb) for h in range(16)], sem=f"QB{qb % 2}")

        loadq(0)
        for qb in range(8):
            qsl = slice(qb * 512, (qb + 1) * 512)
            QT = QB[qb % 2]
            if qb + 1 < 8:
                loadq(qb + 1)
            P.dma("sync", xt[:], fm(xa[:, qsl]), writes=["xt"], sem="xt")
            steps = [(h, kp) for h in range(16) for kp in range(17)]

            def S(i):
                h, kp = steps[i]
                g, oc, h2 = h // 4, h // 2, h % 2
                for e_ in range(2):
                    kt = 2 * kp + e_
                    P.mm(psS[i % 2][:, e_ * 512:(e_ + 1) * 512], Kz[h2][:, g, kt * 128:(kt + 1) * 128], QT[:, oc, :], True, True,
                         ["Kz", ("QT", 2 * oc, qb), ("QT", 2 * oc + 1, qb)], [f"psS{i % 2}"])

            def epi_a(h):
                o = h % 2
                P.cp("vector", osb[o][:], psO[o][:], [f"psO{o}"], [f"osb{o}"])

            def epi_b(h):
                oc, h2, o = h // 2, h % 2, h % 2
                hs = slice(h2 * 64, h2 * 64 + 64)
                P.mm(psB[:, :], sel[:, h2, :], osb[o][:], True, True, ["sel", f"osb{o}"], ["psB"])
                P.act(rb[o][hs, :], psB[hs, :], AF.Ln, ["psB"], [f"rb{o}"])
                P.act(rb[o][hs, :], rb[o][hs, :], AF.Exp, [f"rb{o}"], [f"rb{o}"], scale=-1.0)
                P.tt("gpsimd", QT[hs, oc, :], osb[o][hs, :], rb[o][hs, :], ALU.mult, [f"osb{o}", f"rb{o}"], [("QT", h, qb)])

            S(0)
            pend = {}
            for i, (h, kp) in enumerate(steps):
                g, h2, o = h // 4, h % 2, h % 2
                if i + 1 < len(steps):
                    S(i + 1)
                p_ = i % 3
                P.act(PT[p_][:], psS[i % 2][:, :], AF.Exp, [f"psS{i % 2}"], [f"PT{p_}"], scale=0.125)
                v0 = 65 + 65 * g if h2 == 0 else 1 + 65 * g
                for e_ in range(2):
                    kt = 2 * kp + e_
                    P.mm(psO[o][:, :], VA[:, kt, v0:v0 + 128], PT[p_][:, e_ * 512:(e_ + 1) * 512], kt == 0, kt == 33, [f"PT{p_}", "VA"], [f"psO{o}"])
                if kp == 16:
                    epi_a(h)
                    pend[i + 3] = h
                if i in pend:
                    epi_b(pend.pop(i))
            for k in sorted(pend):
                epi_b(pend[k])
            for oc in range(8):
                j = 0
                for c in range(8):
                    P.mm(pX[j][:, :], wo[:, c, oc * 128:(oc + 1) * 128], QT[:, c, :], c == 0, c == 7,
                         ["wo", ("QT", 2 * c, qb), ("QT", 2 * c + 1, qb)], [f"pX{j}"])
                P.stt(xt[:, oc, :], pX[j][:, :], gates[oc], xt[:, oc, :], ALU.mult, ALU.add, [f"pX{j}", "xt", "modv"], ["xt"])
            P.dma("sync", fm(xa[:, qsl]), xt[:], reads=["xt"], writes=[("xa", qb)], sem="xt")


IN_SHAPES = {
    "xin": [D, TT], "cvec": [128, 8, 2], "w_mod": [2, D, 6 * D], "b_mod": [2, 6 * D], "vecs": [128, NV, 8],
    "mlp_w1": [2, D, 4 * D], "mlp_w2": [2, 4 * D, D],
    "rwkv_wr": [D, D], "rwkv_wk": [D, D], "rwkv_wv": [D, D], "rwkv_wo": [D, D],
    "rwkv_w1": [2, D, 64], "rwkv_w2": [2, 64, D], "rwkv_a1": [2, D, 64], "rwkv_a2": [2, 64, D],
    "rwkv_g1": [D, 128], "rwkv_g2": [128, D], "lnw_st": [128, 8, 64], "lnb_st": [128, 8, 64],
    "attn_wq": [D, D], "attn_wkd": [D, 512], "attn_wv": [D, 256], "attn_wo": [D, D],
    "cosT": [128, T], "sinT": [128, T],
    "c_ident": [128, 128], "c_ones": [128, 128], "c_bones": [128, 128], "c_masks": [128, 4, 128],
    "c_perm": [128, 128], "c_rmask": [128, 256], "c_sel": [128, 2, 128],
}


class IO(dict):
    def __init__(self, nc):
        super().__init__()
        self.nc = nc
        self.used = []

    def __missing__(self, k):
        ap = self.nc.dram_tensor(k, IN_SHAPES[k], F32, kind="ExternalInput").ap()
        self[k] = ap
        self.used.append(k)
        return ap

    def scratch(self, name, shape, dtype):
        return self.nc.dram_tensor(name, list(shape), dtype, kind="Internal").ap()

    def output(self, name, shape, dtype=F32):
        return self.nc.dram_tensor(name, list(shape), dtype, kind="ExternalOutput").ap()


def build(stages="all", dbg=None):
    nc = bass.Bass("TRN2", target_bir_lowering=False)
    io = IO(nc)
    P = Prog(nc)
    G = {}
    outs = {}
    stage_init(P, io, G)
    xa = io.scratch("xa", [D, TT], F32)
    hb = io.scratch("hb", [D, TT], BF16)
    if stages == "t_mlp":
        outs["dbg_h"] = io.output("dbg_h", [D, TT], BF16)
        stage_norm(P, io, G, "n_t", io["xin"], ALL_TILES,
                   lambda ic: mod_scalars(G, 0, 1, ic)[0], lambda ic: mod_scalars(G, 0, 1, ic)[1],
                   lambda c0, tw, ic: fm(hb[:, c0:c0 + tw]), BF16)
        with P.phase("copy"):
            P.dma("sync", xa, io["xin"], writes=["xa"], sem="cpa")
            P.dma("sync", outs["dbg_h"], hb, writes=["o"], sem="cpb")
        stage_mlp(P, io, G, 0, ALL_TILES, xa, hb)
        outs["y"] = io.output("y", [D, TT])
        fin = [G["vec"][:, 4, c:c + 1] for c in range(8)]
        stage_norm(P, io, G, "final", xa, ALL_TILES, lambda ic: fin, lambda ic: None,
                   lambda c0, tw, ic: fm(outs["y"][:, c0:c0 + tw]), F32)
    if stages in ("all", "l0", "l1pre"):
        hp = io.scratch("hp", [D, 4608], F32)
        S = rw_scratch(io)
        with P.phase("zpad"):
            z = P.sb([128, 8, 64], F32)
            P.memset("vector", z[:], 0.0, ["z"])
            for k, o in enumerate((0, 64 + T, 4224, 4288 + C)):
                P.dma("sync", fm(hp[:, o:o + 64]), z[:], reads=["z"], writes=[("hpz", k)], sem=f"z{k}")

        def hdst(c0, tw, ic):
            o = 4288 if ic else 64 + c0
            return fm(hp[:, o:o + tw])

        def hbdst(c0, tw, ic):
            return fm(hb[:, c0:c0 + tw])

        def ms(l, kind, which):
            return lambda ic: mod_scalars(G, l, kind, ic)[which]

        stage_norm(P, io, G, "n_mix0", io["xin"], ALL_TILES, ms(0, 0, 0), ms(0, 0, 1), hdst, F32)
        stage_rwkv1a(P, io, G, hp, S)
        stage_rwkv1b(P, io, G, S)
        stage_rwkv2(P, io, G, S, io["xin"], xa)
        stage_norm(P, io, G, "n_mlp0", xa, ALL_TILES, ms(0, 1, 0), ms(0, 1, 1), hbdst, BF16)
        stage_mlp(P, io, G, 0, ALL_TILES, xa, hb)
        if stages == "l0":
            outs["y"] = io.output("y", [D, TT])
            with P.phase("copyout"):
                P.dma("sync", outs["y"], xa, writes=["o"], sem="cpa")
        else:
            stage_norm(P, io, G, "n_mix1", xa, ALL_TILES, ms(1, 0, 0), ms(1, 0, 1), hbdst, BF16)
            with P.scope():
                QT = io.scratch("qtd", [D, T], BF16)
                Kz = [P.ssb([128, 4, TT], BF16, f"Kz{k}") for k in range(2)]
                VA = P.ssb([128, 34, 390], BF16, "VA")
                stage_qkv(P, io, G, hb, QT, Kz, VA)
                stage_attn(P, io, G, QT, Kz, VA, xa)
            if stages == "l1pre":
                outs["y"] = io.output("y", [D, TT])
                with P.phase("copyout"):
                    P.dma("sync", outs["y"], xa, writes=["o"], sem="cpa")
            else:
                stage_norm(P, io, G, "n_mlp1", xa, LAT_TILES, ms(1, 1, 0), ms(1, 1, 1), hbdst, BF16)
                stage_mlp(P, io, G, 1, LAT_TILES, xa, hb)
                outs["y"] = io.output("y", [D, T])
                fin = [G["vec"][:, 4, c:c + 1] for c in range(8)]
                stage_norm(P, io, G, "final", xa, LAT_TILES, lambda ic: fin, lambda ic: None,
                           lambda c0, tw, ic: fm(outs["y"][:, c0:c0 + tw]), F32)
    if stages == "t_rwkv":
        hp = io.scratch("hp", [D, 4608], F32)
        S = rw_scratch(io)
        with P.phase("zpad"):
            z = P.sb([128, 8, 64], F32)
            P.memset("vector", z[:], 0.0, ["z"])
            for k, o in enumerate((0, 64 + T, 4224, 4288 + C)):
                P.dma("sync", fm(hp[:, o:o + 64]), z[:], reads=["z"], writes=[("hpz", k)], sem=f"z{k}")
        def hdst(c0, tw, ic):
            o = 4288 if ic else 64 + c0
            return fm(hp[:, o:o + tw])
        stage_norm(P, io, G, "n_mix0", io["xin"], ALL_TILES,
                   lambda ic: mod_scalars(G, 0, 0, ic)[0], lambda ic: mod_scalars(G, 0, 0, ic)[1], hdst, F32)
        stage_rwkv1(P, io, G, hp, S)
        stage_rwkv2(P, io, G, S, io["xin"], xa)
        outs["y"] = io.output("y", [D, TT])
        with P.phase("copyout"):
            P.dma("sync", outs["y"], xa, writes=["o"], sem="cpa")
    P.close()
    return nc, io.used, list(outs.keys()), P


def fmv(v):
    return np.ascontiguousarray(np.asarray(v, np.float32).reshape(8, 128).T)


def host_consts():
    c = {}
    c["c_ident"] = np.eye(128, dtype=np.float32)
    c["c_ones"] = np.ones((128, 128), np.float32)
    blk = np.zeros((128, 128), np.float32)
    blk[:64, :64] = 1
    blk[64:, 64:] = 1
    c["c_bones"] = blk
    i = np.arange(64)
    us = (i[:, None] < i[None, :]).astype(np.float32)
    ui = (i[:, None] <= i[None, :]).astype(np.float32)
    m = np.zeros((128, 4, 128), np.float32)
    for k, mk in enumerate([us, ui, us.T, ui.T]):
        m[:64, k, :64] = mk
        m[64:, k, 64:] = mk
    c["c_masks"] = m
    Pm = np.zeros((128, 128), np.float32)
    for d in range(128):
        if d % 32 < 16:
            Pm[d, d + 16] = -1.0
        else:
            Pm[d, d - 16] = 1.0
    c["c_perm"] = np.ascontiguousarray(Pm.T)
    sel = np.zeros((128, 2, 128), np.float32)
    sel[64, 0, :] = 1.0
    sel[63, 1, :] = 1.0
    c["c_sel"] = sel
    rm = np.ones((128, 256), np.float32)
    rm[:, ::64] = 0
    c["c_rmask"] = rm
    t = np.arange(T)
    row = (t // 64).astype(np.float32)
    col = (t % 64).astype(np.float32)
    freqs = (np.float32(10000.0) ** (-np.arange(0, 32, 2, dtype=np.float32) / np.float32(32))).astype(np.float32)
    ang = np.zeros((64, T), np.float32)
    for d in range(64):
        pos = row if d < 32 else col
        ang[d] = pos * freqs[d % 16]
    c["cosT"] = np.ascontiguousarray(np.concatenate([np.cos(ang), np.cos(ang)], 0).astype(np.float32))
    c["sinT"] = np.ascontiguousarray(np.concatenate([np.sin(ang), np.sin(ang)], 0).astype(np.float32))
    return c


def host_inputs(inp, b):
    f = lambda k: np.asarray(inp[k], np.float32)
    d = {}
    d["xin"] = np.ascontiguousarray(np.concatenate([f("x")[b].T, f("ctx")[b].T], axis=1))
    d["cvec"] = np.ascontiguousarray(np.stack([fmv(f("c")[b]), fmv(f("c_ctx"))], axis=-1))
    return d


def host_shared(inp):
    f = lambda k: np.asarray(inp[k], np.float32)
    s = dict(host_consts())
    s["w_mod"] = f("w_mod")
    s["b_mod"] = f("b_mod")
    vl = [f("norm_mix")[0], f("norm_mix")[1], f("norm_mlp")[0], f("norm_mlp")[1], f("final_norm")]
    vl += [f("rwkv_mu")[0, j] for j in range(6)]
    vl += [f("rwkv_w0")[0, 0], f("rwkv_w0")[0, 1], f("rwkv_a0")[0, 0], f("rwkv_a0")[0, 1]]
    vl += [f("rwkv_k_k")[0], f("rwkv_k_a")[0], np.zeros(D, np.float32), f("rwkv_r_k")[0].reshape(-1)]
    vl += [np.tile(f("attn_q_norm")[0], 16), np.tile(f("attn_k_norm")[0], 16)]
    assert len(vl) == NV
    s["vecs"] = np.ascontiguousarray(np.stack([fmv(v) for v in vl], axis=1))
    s["mlp_w1"] = f("mlp_w1")
    s["mlp_w2"] = f("mlp_w2")
    for k in ("wr", "wk", "wv", "wo", "w1", "w2", "a1", "a2", "g1", "g2"):
        s["rwkv_" + k] = f("rwkv_" + k)[0]
    lw = f("rwkv_ln_w")[0].reshape(8, 2, 64)
    lb = f("rwkv_ln_b")[0].reshape(8, 2, 64)
    s["lnw_st"] = np.ascontiguousarray(np.repeat(lw.transpose(1, 0, 2), 64, axis=0))
    s["lnb_st"] = np.ascontiguousarray(np.repeat(lb.transpose(1, 0, 2), 64, axis=0))
    wqkv = f("attn_wqkv")[0]
    s["attn_wq"] = np.ascontiguousarray(wqkv[:, :1024])
    wk = wqkv[:, 1024:1280].reshape(D, 4, 64)
    s["attn_wkd"] = np.ascontiguousarray(np.concatenate([wk, wk], axis=2).reshape(D, 512))
    s["attn_wv"] = np.ascontiguousarray(wqkv[:, 1280:1536])
    s["attn_wo"] = f("attn_wo")[0]
    return s


_CACHE = {}


def kernel(**inputs):
    if "prog" not in _CACHE:
        _CACHE["prog"] = build("all")
    nc, used, outnames, _ = _CACHE["prog"]
    shared = host_shared(inputs)
    in_maps = []
    for b in range(NCORES):
        hi = host_inputs(inputs, b)
        hi.update(shared)
        in_maps.append({k: hi[k] for k in used})
    res = run_bass_kernel_spmd(nc, in_maps, core_ids=list(range(NCORES)))
    out = np.stack([np.ascontiguousarray(res.results[b]["y"].T) for b in range(NCORES)], axis=0)
    return out.astype(np.float32)
```

```python
from contextlib import ExitStack, contextmanager
import re as re_mod
import numpy as np
import concourse.bass as bass
import concourse.mybir as mybir
from concourse.bass_utils import run_bass_kernel_spmd

F32 = mybir.dt.float32
BF16 = mybir.dt.bfloat16
AF = mybir.ActivationFunctionType
ALU = mybir.AluOpType
AX = mybir.AxisListType

D = 1024
T = 4096
C = 256
TT = T + C
NCORES = 8
C0 = float(np.exp(-0.5))
NV = 21
ENGS = ("tensor", "vector", "scalar", "gpsimd", "sync")


class Prog:
    def __init__(self, nc):
        self.nc = nc
        self.ges = ExitStack()
        self.sems = {}
        self.cnt = {}
        self.dpool = {False: [], True: []}
        self.seen = {e: {} for e in ENGS}
        self.n = 0
        self.pes = None
        self.total_ops = 0

    def _alloc(self, es, fn, shape, dtype, name):
        self.n += 1
        return es.enter_context(fn(name or f"t{self.n}", list(shape), dtype))

    def gsb(self, shape, dtype, name=None):
        return self._alloc(self.ges, self.nc.sbuf_tensor, shape, dtype, name)

    def sb(self, shape, dtype, name=None):
        return self._alloc(self.pes, self.nc.sbuf_tensor, shape, dtype, name)

    @contextmanager
    def scope(self):
        self.ses = ExitStack()
        yield self
        self.ses.close()
        self.ses = None

    def ssb(self, shape, dtype, name=None):
        return self._alloc(self.ses, self.nc.sbuf_tensor, shape, dtype, name)

    def ps(self, shape, dtype, name=None):
        return self._alloc(self.pes, self.nc.psum_tensor, shape, dtype, name)

    @contextmanager
    def phase(self, name):
        self.ops = []
        self.last_w = {}
        self.readers = {}
        self.last_dma = {}
        self.pes = ExitStack()
        self.pname = name
        yield self
        self._emit()
        self.pes.close()
        self.pes = None

    _PSUM_RE = re_mod.compile(r"^(pp|pa|pb|pq|pf|ps\w*|pX|py|pS|ptr|pw)\d*$")

    ns = None
    ns_set = frozenset()

    def _deps(self, reads, writes):
        if self.ns is not None:
            reads = tuple((r, self.ns) if r in self.ns_set else r for r in reads)
            writes = tuple((w, self.ns) if w in self.ns_set else w for w in writes)
        extra = tuple(r for r in reads if isinstance(r, str) and self._PSUM_RE.match(r) and r not in writes)
        if extra:
            writes = tuple(writes) + extra
        deps = {}
        for r in reads:
            if r in self.last_w:
                deps.setdefault(self.last_w[r], set()).add("RAW")
        for w in writes:
            if w in self.last_w:
                deps.setdefault(self.last_w[w], set()).add("WAW")
            for rd in self.readers.get(w, ()):
                deps.setdefault(rd, set()).add("WAR")
        idx = len(self.ops)
        for r in reads:
            self.readers.setdefault(r, []).append(idx)
        for w in writes:
            self.last_w[w] = idx
            self.readers[w] = []
        return deps

    def op(self, eng, fn, reads=(), writes=()):
        deps = self._deps(tuple(reads), tuple(writes))
        self.ops.append(dict(eng=eng, fn=fn, deps=deps, dma=None))
        return len(self.ops) - 1

    def dma(self, queue, out, in_, reads=(), writes=(), sem=None):
        deps = self._deps(tuple(reads), tuple(writes))
        prev = self.last_dma.get(sem)
        if prev is not None:
            deps.setdefault(prev, set()).add("SER")
        idx = len(self.ops)
        self.last_dma[sem] = idx
        self.ops.append(dict(eng=queue, fn=lambda e: e.dma_start(out=out, in_=in_), deps=deps, dma=sem))
        return idx

    def _emit(self):
        nc = self.nc
        ops = self.ops
        if self.last_dma:
            ops.append(dict(eng="sync", fn=None, deps={i: {"FIN"} for i in self.last_dma.values()}, dma=None))
        self.total_ops += len(ops)

        def needs_wait(x, d, kinds):
            if d["dma"] is not None or x["dma"] is not None:
                return True
            if d["eng"] != x["eng"]:
                return True
            if x["eng"] == "tensor":
                return False
            return bool(kinds & {"RAW", "FIN"})

        signal = [False] * len(ops)
        for x in ops:
            for di, kinds in x["deps"].items():
                d = ops[di]
                if d["dma"] is None and needs_wait(x, d, kinds):
                    signal[di] = True
        dkeys = {}
        nk = {False: 0, True: 0}
        for o in ops:
            if o["dma"] is not None and o["dma"] not in dkeys:
                sw = o["eng"] == "gpsimd"
                dkeys[o["dma"]] = (sw, nk[sw])
                nk[sw] += 1
        for sw in (False, True):
            while len(self.dpool[sw]) < nk[sw]:
                h = self.ges.enter_context(nc.semaphore(f"dq{int(sw)}_{len(self.dpool[sw])}"))
                self.dpool[sw].append([h, 0])
        for e in ENGS:
            if e not in self.sems:
                self.sems[e] = self.ges.enter_context(nc.semaphore(f"e_{e}"))
        token = [None] * len(ops)
        for i, o in enumerate(ops):
            if o["dma"] is not None:
                dk = dkeys[o["dma"]]
                slot = self.dpool[dk[0]][dk[1]]
                slot[1] += 16
                token[i] = (("d", dk), slot[1])
            elif signal[i]:
                self.cnt[o["eng"]] = self.cnt.get(o["eng"], 0) + 1
                token[i] = (("e", o["eng"]), self.cnt[o["eng"]])
        per_eng = {e: [] for e in ENGS}
        for i, o in enumerate(ops):
            per_eng[o["eng"]].append(i)

        def semh(key):
            return self.dpool[key[1][0]][key[1][1]][0] if key[0] == "d" else self.sems[key[1]]

        def run(engname, eng):
            seen = self.seen[engname]
            for i in per_eng[engname]:
                o = ops[i]
                waits = {}
                for di, kinds in o["deps"].items():
                    d = ops[di]
                    if not needs_wait(o, d, kinds):
                        continue
                    key, val = token[di]
                    if waits.get(key, 0) < val:
                        waits[key] = val
                for key, val in waits.items():
                    if seen.get(key, 0) >= val:
                        continue
                    seen[key] = val
                    eng.wait_ge(semh(key), val)
                if o["fn"] is None:
                    continue
                ins = o["fn"](eng)
                if o["dma"] is not None:
                    ins.then_inc(semh(token[i][0]), 16)
                elif signal[i]:
                    ins.then_inc(self.sems[engname], 1)

        with nc.Block() as block:
            @block.sync
            def _(e):
                run("sync", e)

            @block.tensor
            def _(e):
                run("tensor", e)

            @block.vector
            def _(e):
                run("vector", e)

            @block.scalar
            def _(e):
                run("scalar", e)

            @block.gpsimd
            def _(e):
                run("gpsimd", e)

    def close(self):
        self.ges.close()

    def mm(self, out, lhsT, rhs, start, stop, r, w):
        self.op("tensor", lambda e: e.matmul(out, lhsT=lhsT, rhs=rhs, start=start, stop=stop), r, w)

    def tr(self, out, in_, ident, r, w):
        self.op("tensor", lambda e: e.transpose(out, in_, ident), r, w)

    def tt(self, eng, out, in0, in1, op, r, w):
        self.op(eng, lambda e: e.tensor_tensor(out=out, in0=in0, in1=in1, op=op), r, w)

    def ts(self, eng, out, in0, s1, s2, op0, op1, r, w):
        if op1 is None:
            self.op(eng, lambda e: e.tensor_scalar(out=out, in0=in0, scalar1=s1, scalar2=None, op0=op0), r, w)
        else:
            self.op(eng, lambda e: e.tensor_scalar(out=out, in0=in0, scalar1=s1, scalar2=s2, op0=op0, op1=op1), r, w)

    def stt(self, out, in0, scalar, in1, op0, op1, r, w):
        self.op("vector", lambda e: e.scalar_tensor_tensor(out=out, in0=in0, scalar=scalar, in1=in1, op0=op0, op1=op1), r, w)

    def act(self, out, in_, func, r, w, bias=None, scale=None):
        kw = {}
        if bias is not None:
            kw["bias"] = bias
        if scale is not None:
            kw["scale"] = scale
        self.op("scalar", lambda e: e.activation(out=out, in_=in_, func=func, **kw), r, w)

    def cp(self, eng, out, in_, r, w):
        if eng == "scalar":
            self.op(eng, lambda e: e.activation(out=out, in_=in_, func=AF.Copy), r, w)
        else:
            self.op(eng, lambda e: e.tensor_copy(out=out, in_=in_), r, w)

    def memset(self, eng, ap, val, w):
        self.op(eng, lambda e: e.memset(ap, val), (), w)


def fm(ap2d):
    return ap2d.rearrange("(c p) n -> p c n", p=128)


LAT_TILES = [(i * 512, 512, False) for i in range(8)]
ALL_TILES = LAT_TILES + [(T, 256, True)]


def stage_init(P, io, G):
    nc = P.nc
    G["identf"] = P.gsb([128, 128], F32, "identf")
    G["identb"] = P.gsb([128, 128], BF16, "identb")
    G["onesb"] = P.gsb([128, 128], BF16, "onesb")
    G["bones"] = P.gsb([128, 128], BF16, "bones")
    G["masks"] = P.gsb([128, 4, 128], BF16, "masks")
    G["perm"] = P.gsb([128, 128], BF16, "perm")
    G["rmask"] = P.gsb([128, 256], F32, "rmask")
    G["vec"] = P.gsb([128, NV, 8], F32, "vec")
    G["modv"] = P.gsb([128, 2, 6, 8, 2], F32, "modv")
    G["gg"] = P.gsb([128, 2, 2, 8, 2], F32, "gg")
    with P.phase("init"):
        P.dma("sync", G["identf"][:], io["c_ident"], writes=["identf"], sem="identf")
        P.dma("sync", G["rmask"][:], io["c_rmask"], writes=["rmask"], sem="rmask")
        P.dma("sync", G["vec"][:], io["vecs"], writes=["vec"], sem="vec")
        P.dma("gpsimd", G["identb"][:], io["c_ident"], writes=["identb"], sem="identb")
        P.dma("gpsimd", G["onesb"][:], io["c_ones"], writes=["onesb"], sem="onesb")
        P.dma("gpsimd", G["bones"][:], io["c_bones"], writes=["bones"], sem="bones")
        P.dma("gpsimd", G["masks"][:], io["c_masks"], writes=["masks"], sem="masks")
        P.dma("gpsimd", G["perm"][:], io["c_perm"], writes=["perm"], sem="perm")
        vec = G["vec"]
        P.ts("vector", vec[:, 17, :], vec[:, 16, :], -1.0, 1.0, ALU.mult, ALU.add, ["vec"], ["vec"])
        sv = P.sb([128, 8, 2], F32)
        svs = P.sb([128, 8, 2], F32)
        P.dma("sync", sv[:], io["cvec"], writes=["sv"], sem="sv")
        P.act(svs[:], sv[:], AF.Silu, ["sv"], ["svs"])
        brow = P.sb([2, 2 * 6144], F32)
        row = P.sb([2, 2 * 6144], F32)
        P.dma("sync", brow[:], io["b_mod"].rearrange("l n -> (l n)").partition_broadcast(2), writes=["brow"], sem="brow")
        wt = [P.sb([128, 8, 512], F32) for _ in range(2)]
        psr = [P.ps([128, 512], F32) for _ in range(2)]
        pst = P.ps([128, 512], F32)
        k = 0
        for l in range(2):
            for nb in range(12):
                b = k % 2
                k += 1
                P.dma("sync", wt[b][:], fm(io["w_mod"][l, :, nb * 512:(nb + 1) * 512]), writes=[f"wt{b}"], sem=f"wt{b}")
                for c in range(8):
                    P.mm(psr[b][0:2, :], svs[:, c, :], wt[b][:, c, :], c == 0, c == 7, ["svs", f"wt{b}"], [f"psr{b}"])
                o = l * 6144 + nb * 512
                P.tt("vector", row[:, o:o + 512], psr[b][0:2, :], brow[:, o:o + 512], ALU.add, [f"psr{b}", "brow"], ["row"])
        for l in range(2):
            for blk in range(48):
                o = l * 6144 + blk * 128
                P.tr(pst[:, l * 96 + blk * 2:l * 96 + blk * 2 + 2], row[0:2, o:o + 128], G["identf"][0:2, 0:2], ["row", "identf"], ["pst"])
        P.cp("vector", G["modv"][:].rearrange("p l m c j -> p (l m c j)"), pst[:, 0:192], ["pst"], ["modv"])
        modv, gg = G["modv"], G["gg"]
        for l in range(2):
            for kind in range(2):
                sc = modv[:, l, 1 + 3 * kind, :, :]
                nv = vec[:, (0 if kind == 0 else 2) + l, :].unsqueeze(2).broadcast_to([128, 8, 2])
                P.ts("vector", gg[:, l, kind, :, :], sc, 1.0, None, ALU.add, None, ["modv"], ["gg"])
                P.tt("vector", gg[:, l, kind, :, :], gg[:, l, kind, :, :], nv, ALU.mult, ["gg", "vec"], ["gg"])


def mod_scalars(G, l, kind, isctx):
    j = 1 if isctx else 0
    gains = [G["gg"][:, l, kind, c, j:j + 1] for c in range(8)]
    shifts = [G["modv"][:, l, 3 * kind, c, j:j + 1] for c in range(8)]
    gates = [G["modv"][:, l, 3 * kind + 2, c, j:j + 1] for c in range(8)]
    return gains, shifts, gates


def stage_norm(P, io, G, name, src, tiles, gains_fn, shifts_fn, dst_fn, out_dtype):
    with P.phase(name):
        xt = [P.sb([128, 8, 512], F32) for _ in range(2)]
        sq = P.sb([128, 8, 512], BF16)
        lnv = P.sb([128, 512], F32)
        rstd = P.sb([128, 512], F32)
        tmp = [P.sb([128, 512], F32) for _ in range(2)]
        ho = [P.sb([128, 8, 512], out_dtype) for _ in range(2)]
        ps = [P.ps([128, 512], F32) for _ in range(2)]

        def load(i):
            c0, tw, _ = tiles[i]
            b = i % 2
            P.dma("sync", xt[b][:, :, :tw], fm(src[:, c0:c0 + tw]), writes=[f"xt{b}"], sem=f"xt{b}")

        load(0)
        for i, (c0, tw, isctx) in enumerate(tiles):
            b = i % 2
            if i + 1 < len(tiles):
                load(i + 1)
            gains = gains_fn(isctx)
            shifts = shifts_fn(isctx)
            P.act(sq[:, :, :tw], xt[b][:, :, :tw], AF.Square, [f"xt{b}"], ["sq"])
            for c in range(8):
                P.mm(ps[b][:, :tw], G["onesb"][:], sq[:, c, :tw], c == 0, c == 7, ["sq", "onesb"], [f"ps{b}"])
            P.act(lnv[:, :tw], ps[b][:, :tw], AF.Ln, [f"ps{b}"], ["lnv"], bias=1e-6, scale=1.0 / D)
            P.act(rstd[:, :tw], lnv[:, :tw], AF.Exp, ["lnv"], ["rstd"], scale=-0.5)
            for c in range(8):
                if shifts is None:
                    P.stt(ho[b][:, c, :tw], xt[b][:, c, :tw], gains[c], rstd[:, :tw], ALU.mult, ALU.mult,
                          [f"xt{b}", "rstd", "vec", "gg"], [f"ho{b}"])
                else:
                    t = tmp[c % 2]
                    P.stt(t[:, :tw], xt[b][:, c, :tw], gains[c], rstd[:, :tw], ALU.mult, ALU.mult,
                          [f"xt{b}", "rstd", "vec", "gg"], [f"tmp{c % 2}"])
                    P.act(ho[b][:, c, :tw], t[:, :tw], AF.Identity, [f"tmp{c % 2}", "modv"], [f"ho{b}"], bias=shifts[c])
            P.dma("sync", dst_fn(c0, tw, isctx), ho[b][:, :, :tw], reads=[f"ho{b}"], writes=[("dst", i)], sem=f"ho{b}")


def stage_mlp(P, io, G, l, tiles, xa, hb):
    for half in range(2):
        with P.phase(f"mlp{l}{half}"):
            w1 = P.sb([128, 8, 2048], BF16)
            w2 = P.sb([128, 16, 1024], BF16)
            for q in range(2):
                P.dma("gpsimd", w1[:, :, q * 1024:(q + 1) * 1024],
                      fm(io["mlp_w1"][l, :, half * 2048 + q * 1024: half * 2048 + (q + 1) * 1024]), writes=["w1"], sem=f"w1{q}")
                P.dma("gpsimd", w2[:, q * 8:(q + 1) * 8, :],
                      io["mlp_w2"][l, half * 2048 + q * 1024: half * 2048 + (q + 1) * 1024, :].rearrange("(f p) n -> p f n", p=128),
                      writes=["w2"], sem=f"w2{q}")
            xt = [P.sb([128, 8, 512], F32) for _ in range(2)]
            ht = [P.sb([128, 8, 512], BF16) for _ in range(2)]
            h1 = P.sb([128, 16, 512], BF16)
            r1 = [P.sb([128, 512], F32) for _ in range(2)]
            ps = [P.ps([128, 512], F32) for _ in range(4)]

            def load(i):
                c0, tw, _ = tiles[i]
                b = i % 2
                P.dma("sync", ht[b][:, :, :tw], fm(hb[:, c0:c0 + tw]), writes=[f"ht{b}"], sem=f"ht{b}")
                P.dma("sync", xt[b][:, :, :tw], fm(xa[:, c0:c0 + tw]), reads=[("xa", i)], writes=[f"xt{b}"], sem=f"xt{b}")

            load(0)
            for i, (c0, tw, isctx) in enumerate(tiles):
                b = i % 2
                if i + 1 < len(tiles):
                    load(i + 1)
                _, _, gates = mod_scalars(G, l, 1, isctx)
                for fc in range(16):
                    pb = fc % 2
                    for c in range(8):
                        P.mm(ps[pb][:, :tw], w1[:, c, fc * 128:(fc + 1) * 128], ht[b][:, c, :tw], c == 0, c == 7,
                             ["w1", f"ht{b}"], [f"ps{pb}"])
                    P.act(r1[pb][:, :tw], ps[pb][:, :tw], AF.Relu, [f"ps{pb}"], [f"r1{pb}"])
                    P.tt("gpsimd", h1[:, fc, :tw], r1[pb][:, :tw], r1[pb][:, :tw], ALU.mult, [f"r1{pb}"], [("h1", fc)])
                for oc in range(8):
                    pb = 2 + oc % 2
                    for fc in range(16):
                        P.mm(ps[pb][:, :tw], w2[:, fc, oc * 128:(oc + 1) * 128], h1[:, fc, :tw], fc == 0, fc == 15,
                             ["w2", ("h1", fc)], [f"ps{pb}"])
                    P.stt(xt[b][:, oc, :tw], ps[pb][:, :tw], gates[oc], xt[b][:, oc, :tw], ALU.mult, ALU.add,
                          [f"ps{pb}", f"xt{b}", "modv"], [f"xt{b}"])
                P.dma("sync", fm(xa[:, c0:c0 + tw]), xt[b][:, :, :tw], reads=[f"xt{b}"], writes=[("xa", i)], sem=f"xt{b}")


RW_ORDER1 = [(True, 0)] + [(False, i) for i in range(16)]
RW_ORDER2 = [(True, 0)] + [(False, i) for i in range(15, -1, -1)]


def rw_scratch(io):
    S = {}
    S["yp"] = io.scratch("rw_yp", [17, 8, 128, 256], F32)
    S["sadd"] = io.scratch("rw_sadd", [17, 8, 128, 256], F32)
    S["vst"] = io.scratch("rw_vst", [17, 8, 128, 256], F32)
    S["gst"] = io.scratch("rw_gst", [17, 8, 128, 256], F32)
    S["gyb"] = io.scratch("rw_gyb", [17, 8, 128, 512], BF16)
    S["gsb"] = io.scratch("rw_gsb", [17, 8, 128, 512], BF16)
    S["gamb"] = io.scratch("rw_gamb", [17, 128, 32], F32)
    S["bon"] = io.scratch("rw_bon", [17, 128, 32], F32)
    S["ops"] = io.scratch("rw_ops", [17, 8, 128, 2048], BF16)
    S["vb"] = io.scratch("rw_vb", [17, 8, 128, 256], BF16)
    S["gam"] = io.scratch("rw_gam", [17, 8, 128, 8], F32)
    return S


def stage_rwkv1(P, io, G, hp, S, dbg=None):
    vec, masks, identb, identf, bones, onesb, rmask = (G[k] for k in ("vec", "masks", "identb", "identf", "bones", "onesb", "rmask"))
    with P.phase("rwkv1"):
        wr = P.sb([128, 8, 1024], BF16)
        wk = P.sb([128, 8, 1024], BF16)
        wv = P.sb([128, 8, 1024], BF16)
        for w, nm in ((wr, "rwkv_wr"), (wk, "rwkv_wk"), (wv, "rwkv_wv")):
            P.dma("gpsimd", w[:], fm(io[nm]), writes=[nm], sem=nm)
        lw1 = P.sb([128, 8, 128], BF16)
        la1 = P.sb([128, 8, 128], BF16)
        g1 = P.sb([128, 8, 128], BF16)
        for d in range(2):
            P.dma("gpsimd", lw1[:, :, d * 64:(d + 1) * 64], io["rwkv_w1"][d].rearrange("(c p) j -> p c j", p=128), writes=["lw1"], sem=f"lw1{d}")
            P.dma("gpsimd", la1[:, :, d * 64:(d + 1) * 64], io["rwkv_a1"][d].rearrange("(c p) j -> p c j", p=128), writes=["la1"], sem=f"la1{d}")
        P.dma("gpsimd", g1[:], io["rwkv_g1"].rearrange("(c p) j -> p c j", p=128), writes=["g1"], sem="g1")
        w2s = P.sb([128, 1024], BF16)
        a2s = P.sb([128, 1024], BF16)
        g2 = P.sb([128, 1024], BF16)
        P.dma("gpsimd", w2s[:], io["rwkv_w2"].rearrange("d j f -> (d j) f"), writes=["w2s"], sem="w2s")
        P.dma("gpsimd", a2s[:], io["rwkv_a2"].rearrange("d j f -> (d j) f"), writes=["a2s"], sem="a2s")
        P.dma("gpsimd", g2[:], io["rwkv_g2"], writes=["g2"], sem="g2")

        hh = P.sb([128, 8, 384], F32)
        xx = P.sb([128, 8, 256], F32)
        xr = P.sb([128, 8, 256], BF16)
        xk = P.sb([128, 8, 256], BF16)
        xv = P.sb([128, 8, 256], BF16)
        xrot = P.sb([128, 8, 256], BF16)
        lwt = P.sb([128, 256], BF16)
        lat = P.sb([128, 256], BF16)
        sg = P.sb([128, 256], BF16)
        f32t = {}
        for nm in ("r", "k", "sw0", "sw1", "ag0", "ag1", "kq", "lnv", "rs", "kkn", "fac", "kd0", "kd1", "b0", "b1",
                   "L", "Lx", "Lb", "E1", "E2", "E3", "ks"):
            f32t[nm] = P.sb([128, 256], F32, "t_" + nm)
        sqb = P.sb([128, 256], BF16)
        RK = P.sb([128, 4, 2, 64], BF16)
        VTbd = P.sb([128, 4, 128], F32)
        GTbd = P.sb([128, 4, 128], F32)
        Vf = P.sb([128, 4, 64], F32)
        Gf = P.sb([128, 4, 64], F32)
        YPs = P.sb([128, 4, 64], F32)
        SAs = P.sb([128, 4, 64], F32)
        gamb_t = P.sb([128, 8, 4], F32)
        bon_t = P.sb([128, 8, 4], F32)
        Sf = P.sb([128, 8, 64], BF16)
        ARq = [[P.sb([128, 4, 2, 128], BF16, f"AR{q}{d}") for d in range(2)] for q in range(2)]
        KTq = [[P.sb([128, 4, 128], BF16, f"KT{q}{d}") for d in range(2)] for q in range(2)]
        BTq = [[P.sb([128, 4, 128], BF16, f"BT{q}{d}") for d in range(2)] for q in range(2)]
        Vbq = [P.sb([128, 4, 64], BF16, f"Vb{q}") for q in range(3)]
        gamq = [[P.sb([128, 4], F32, f"gam{q}{d}") for d in range(2)] for q in range(3)]
        inv = []
        for d in range(2):
            st = {}
            for nm, shp in (("Atok", [128, 4, 128]), ("Btok", [128, 4, 128]), ("MQ", [128, 4, 256]), ("MWa", [128, 4, 2, 128]),
                            ("MWb", [128, 4, 2, 128]), ("MTa", [128, 4, 128]), ("MTb", [128, 4, 128])):
                st[nm] = P.sb(shp, BF16, f"i{d}_{nm}")
            inv.append(st)
        fin = []
        for q in range(2):
            row = []
            for d in range(2):
                st = {}
                for nm, shp in (("Ktok", [128, 4, 128]), ("NP", [128, 4, 256]), ("XW", [128, 4, 256]), ("NVb", [128, 4, 64]),
                                ("GY", [128, 4, 128]), ("GS", [128, 4, 128])):
                    st[nm] = P.sb(shp, BF16, f"f{q}{d}_{nm}")
                row.append(st)
            fin.append(row)
        ppt = [P.ps([128, 512], F32) for _ in range(2)]
        pp = [t_[:, 0:256] for t_ in ppt]
        pf = P.ps([128, 512], F32)
        pb = [P.ps([128, 512], F32) for _ in range(5)]
        cnt = {"pp": 0, "pb": 0}
        nmod = {"pp": 2, "pb": 5}

        def nxt(kind):
            i = cnt[kind] % nmod[kind]
            cnt[kind] += 1
            return i

        for q in range(2):
            for d in range(2):
                P.memset("gpsimd", ARq[q][d][:], 0.0, [f"AR{q}{d}"])
                P.memset("gpsimd", KTq[q][d][:], 0.0, [f"KT{q}{d}"])
                P.memset("gpsimd", BTq[q][d][:], 0.0, [f"BT{q}{d}"])
        P.memset("gpsimd", RK[:], 0.0, ["RK"])
        P.memset("gpsimd", VTbd[:], 0.0, ["VTbd"])
        P.memset("gpsimd", GTbd[:], 0.0, ["GTbd"])
        P.memset("gpsimd", Sf[:], 0.0, [("Sf", p) for p in range(8)])

        def v3(ap):
            return ap.rearrange("p (u s) -> p u s", s=64)

        def u128(ap):
            return ap.rearrange("p (u x) -> p u x", x=128)

        def load_hh(ti):
            isctx, idx = RW_ORDER1[ti]
            off = 4288 if isctx else 64 + 256 * idx
            P.dma("sync", hh[:], fm(hp[:, off - 64: off + 320]), writes=["hh"], sem="hh")

        def proj8(w_cols_fn, xb, bn, extra_r):
            i = nxt("pp")
            for c in range(8):
                P.mm(pp[i], w_cols_fn(c), xb[:, c, :], c == 0, c == 7, [(bn, c)] + extra_r, [f"pp{i}"])
            return i

        def tprep(ti):
            isctx, idx = RW_ORDER1[ti]
            hc = hh[:, :, 64:320]
            XXW = [("xx", c) for c in range(8)]
            if not isctx:
                h4 = hh[:, :, 64:320].rearrange("p c (r w) -> p c r w", w=64)
                x4 = xx[:].rearrange("p c (r w) -> p c r w", w=64)
                P.tt("vector", x4[:, 0:2, :, 1:64], h4[:, 0:2, :, 0:63], h4[:, 0:2, :, 1:64], ALU.subtract, ["hh"], XXW[0:2])
                P.ts("gpsimd", x4[:, 0:2, :, 0:1], h4[:, 0:2, :, 0:1], -1.0, 0.0, ALU.mult, ALU.add, ["hh"], [("xxe", 0)])
                P.tt("vector", x4[:, 2:4, :, 0:63], h4[:, 2:4, :, 1:64], h4[:, 2:4, :, 0:63], ALU.subtract, ["hh"], XXW[2:4])
                P.ts("gpsimd", x4[:, 2:4, :, 63:64], h4[:, 2:4, :, 63:64], -1.0, 0.0, ALU.mult, ALU.add, ["hh"], [("xxe", 1)])
                P.tt("gpsimd", xx[:, 4:6, :], hh[:, 4:6, 0:256], hh[:, 4:6, 64:320], ALU.subtract, ["hh"], XXW[4:6])
                P.tt("gpsimd", xx[:, 6:8, :], hh[:, 6:8, 128:384], hh[:, 6:8, 64:320], ALU.subtract, ["hh"], XXW[6:8])
            else:
                P.tt("vector", xx[:, 0:4, :], hh[:, 0:4, 63:319], hh[:, 0:4, 64:320], ALU.subtract, ["hh"], XXW[0:4] + [("xxe", 0)])
                P.tt("gpsimd", xx[:, 4:8, :], hh[:, 4:8, 65:321], hh[:, 4:8, 64:320], ALU.subtract, ["hh"], XXW[4:8] + [("xxe", 1)])
            yield

            def mk_xj(j, buf, bn):
                for c in range(8):
                    P.stt(buf[:, c, :], xx[:, c, :], vec[:, 5 + j, c:c + 1], hc[:, c, :], ALU.mult, ALU.add,
                          [("xx", c), ("xxe", 0), ("xxe", 1), "hh", "vec"], [(bn, c)])

            mk_xj(1, xrot, "xrot")
            yield
            i = proj8(lambda c: lw1[:, c, :], xrot, "xrot", ["lw1"])
            P.act(lwt[:], pp[i], AF.Tanh, [f"pp{i}"], ["lwt"])
            yield
            mk_xj(4, xrot, "xrot")
            yield
            i = proj8(lambda c: la1[:, c, :], xrot, "xrot", ["la1"])
            P.cp("scalar", lat[:], pp[i], [f"pp{i}"], ["lat"])
            yield
            mk_xj(5, xrot, "xrot")
            yield
            i = proj8(lambda c: g1[:, c, :], xrot, "xrot", ["g1"])
            P.act(sg[:], pp[i], AF.Sigmoid, [f"pp{i}"], ["sg"])
            yield
            mk_xj(0, xr, "xr")
            yield
            mk_xj(2, xk, "xk")
            yield
            mk_xj(3, xv, "xv")
            if ti + 1 < len(RW_ORDER1):
                load_hh(ti + 1)
            yield

        def prep(ti, oc, q, z):
            isctx, idx = RW_ORDER1[ti]
            tg = 16 if isctx else idx
            cs = slice(oc * 128, (oc + 1) * 128)
            t = f32t
            AR, KT, BT, Vb, gam = ARq[q], KTq[q], BTq[q], Vbq[z], gamq[z]
            i = proj8(lambda c: wr[:, c, cs], xr, "xr", ["rwkv_wr"])
            P.cp("scalar", t["r"][:], pp[i], [f"pp{i}"], ["r"])
            i = proj8(lambda c: wk[:, c, cs], xk, "xk", ["rwkv_wk"])
            P.cp("scalar", t["k"][:], pp[i], [f"pp{i}"], ["k"])
            i = proj8(lambda c: wv[:, c, cs], xv, "xv", ["rwkv_wv"])
            vt4 = VTbd[:].rearrange("p u (h s) -> p u h s", h=2)
            for h2 in range(2):
                sl = slice(h2 * 64, (h2 + 1) * 64)
                P.cp("scalar", vt4[sl, :, h2, :], v3(pp[i][sl, :]), [f"pp{i}"], ["VTbd"])
            i = nxt("pp")
            P.mm(pp[i], g2[:, cs], sg[:], True, True, ["g2", "sg"], [f"pp{i}"])
            gt4 = GTbd[:].rearrange("p u (h s) -> p u h s", h=2)
            for h2 in range(2):
                sl = slice(h2 * 64, (h2 + 1) * 64)
                P.cp("scalar", gt4[sl, :, h2, :], v3(pp[i][sl, :]), [f"pp{i}"], ["GTbd"])
            yield
            j = nxt("pb")
            for u in range(4):
                P.tr(pb[j][:, u * 128:(u + 1) * 128], VTbd[:, u, :], identf[:], ["VTbd", "identf"], [f"pb{j}"])
            pv = u128(pb[j][:])
            for h2 in range(2):
                sl = slice(h2 * 64, (h2 + 1) * 64)
                P.cp("scalar", Vf[sl, :, :], pv[sl, :, h2 * 64:(h2 + 1) * 64], [f"pb{j}"], ["Vf"])
            P.cp("gpsimd", Vb[:], Vf[:], ["Vf"], [f"Vb{z}"])
            P.dma("sync", S["vst"][tg, oc].rearrange("p (u s) -> p u s", s=64), Vf[:], reads=["Vf"], writes=[("vst", tg, oc)], sem="Vf")
            j = nxt("pb")
            for u in range(4):
                P.tr(pb[j][:, u * 128:(u + 1) * 128], GTbd[:, u, :], identf[:], ["GTbd", "identf"], [f"pb{j}"])
            pv = u128(pb[j][:])
            for h2 in range(2):
                sl = slice(h2 * 64, (h2 + 1) * 64)
                P.cp("scalar", Gf[sl, :, :], pv[sl, :, h2 * 64:(h2 + 1) * 64], [f"pb{j}"], ["Gf"])
            P.dma("sync", S["gst"][tg, oc].rearrange("p (u s) -> p u s", s=64), Gf[:], reads=["Gf"], writes=[("gst", tg, oc)], sem="Gf")
            yield
            for d in range(2):
                dl = slice(d * 64, (d + 1) * 64)
                i = nxt("pp")
                P.mm(pp[i], w2s[dl, cs], lwt[dl, :], True, True, ["w2s", "lwt"], [f"pp{i}"])
                P.act(t[f"sw{d}"][:], pp[i], AF.Sigmoid, [f"pp{i}", "vec"], [f"sw{d}"], bias=vec[:, 11 + d, oc:oc + 1])
                i = nxt("pp")
                P.mm(pp[i], a2s[dl, cs], lat[dl, :], True, True, ["a2s", "lat"], [f"pp{i}"])
                P.act(t[f"ag{d}"][:], pp[i], AF.Sigmoid, [f"pp{i}", "vec"], [f"ag{d}"], bias=vec[:, 13 + d, oc:oc + 1])
            yield
            P.ts("vector", t["kq"][:], t["k"][:], vec[:, 15, oc:oc + 1], None, ALU.mult, None, ["k", "vec"], ["kq"])
            P.act(sqb[:], t["kq"][:], AF.Square, ["kq"], ["sqb"])
            i = nxt("pp")
            P.mm(pp[i], bones[:], sqb[:], True, True, ["bones", "sqb"], [f"pp{i}"])
            P.act(t["lnv"][:], pp[i], AF.Ln, [f"pp{i}"], ["lnv"], bias=1e-12)
            P.act(t["rs"][:], t["lnv"][:], AF.Exp, ["lnv"], ["rs"], scale=-0.5)
            P.tt("gpsimd", t["kkn"][:], t["kq"][:], t["rs"][:], ALU.mult, ["kq", "rs"], ["kkn"])
            for d in range(2):
                sw, ag, kd, bb = t[f"sw{d}"], t[f"ag{d}"], t[f"kd{d}"], t[f"b{d}"]
                EE = "gpsimd" if d == 0 else "vector"
                P.ts(EE, t["fac"][:], ag[:], vec[:, 16, oc:oc + 1], vec[:, 17, oc:oc + 1], ALU.mult, ALU.add, [f"ag{d}", "vec"], ["fac"])
                P.tt(EE, kd[:], t["k"][:], t["fac"][:], ALU.mult, ["k", "fac"], [f"kd{d}"])
                P.tt(EE, bb[:], t["kkn"][:], ag[:], ALU.mult, ["kkn", f"ag{d}"], [f"b{d}"])
                P.op("vector", lambda e, sw=sw: e.tensor_tensor_scan(out=t["L"][:], data0=rmask[:], data1=sw[:], initial=0.0,
                                                                      op0=ALU.mult, op1=ALU.add), [f"sw{d}", "rmask"], ["L"])
                L3 = v3(t["L"][:])
                if d == 0:
                    P.tt(EE, t["Lx"][:], t["L"][:], sw[:], ALU.subtract, ["L", f"sw{d}"], ["Lx"])
                    Li, Lin = t["L"], "L"
                else:
                    P.tt(EE, v3(t["Lx"][:]), L3[:, :, 63:64].broadcast_to([128, 4, 64]), L3, ALU.subtract, ["L"], ["Lx"])
                    P.tt(EE, t["Lb"][:], t["Lx"][:], sw[:], ALU.add, ["Lx", f"sw{d}"], ["Lb"])
                    Li, Lin = t["Lb"], "Lb"
                P.act(t["E1"][:], Li[:], AF.Exp, [Lin], ["E1"], scale=-C0)
                P.act(t["E3"][:], Li[:], AF.Exp, [Lin], ["E3"], scale=C0)
                P.act(t["E2"][:], t["Lx"][:], AF.Exp, ["Lx"], ["E2"], scale=-C0)
                ar5 = AR[d][:].rearrange("p u a (h s) -> p u a h s", h=2)
                kt4 = KT[d][:].rearrange("p u (h s) -> p u h s", h=2)
                bt4 = BT[d][:].rearrange("p u (h s) -> p u h s", h=2)
                for h2 in range(2):
                    sl = slice(h2 * 64, (h2 + 1) * 64)
                    P.stt(ar5[sl, :, 0, h2, :], v3(t["kkn"][sl, :]), -1.0, v3(t["E2"][sl, :]), ALU.mult, ALU.mult, ["kkn", "E2"], [f"AR{q}{d}"])
                    P.tt(EE, ar5[sl, :, 1, h2, :], v3(t["r"][sl, :]), v3(t["E1"][sl, :]), ALU.mult, ["r", "E1"], [f"AR{q}{d}"])
                    P.tt(EE, kt4[sl, :, h2, :], v3(kd[sl, :]), v3(t["E3"][sl, :]), ALU.mult, [f"kd{d}", "E3"], [f"KT{q}{d}"])
                    P.tt(EE, bt4[sl, :, h2, :], v3(bb[sl, :]), v3(t["E3"][sl, :]), ALU.mult, [f"b{d}", "E3"], [f"BT{q}{d}"])
                E13 = v3(t["E1"][:])
                gsrc = E13[:, :, 63] if d == 0 else E13[:, :, 0]
                P.cp("vector", gam[d][:], gsrc, ["E1"], [f"gam{z}{d}"])
                if d == 1:
                    P.cp("gpsimd", gamb_t[:, oc, :], gam[1][:], [f"gam{z}1"], ["gamb_t"])
                yield
            P.tt("gpsimd", t["ks"][:], t["kd0"][:], t["kd1"][:], ALU.add, ["kd0", "kd1"], ["ks"])
            for h2 in range(2):
                sl = slice(h2 * 64, (h2 + 1) * 64)
                P.stt(RK[sl, :, h2, :], v3(t["r"][sl, :]), vec[sl, 18, oc:oc + 1], v3(t["ks"][sl, :]), ALU.mult, ALU.mult, ["r", "ks", "vec"], ["RK"])
            i = nxt("pp")
            for u in range(4):
                P.mm(pp[i][:, u:u + 1], RK[:, u, :, :].rearrange("p h s -> p (h s)"), onesb[:, 0:1], True, True, ["RK", "onesb"], [f"pp{i}"])
            P.cp("scalar", bon_t[:, oc, :], pp[i][:, 0:4], [f"pp{i}"], ["bon_t"])
            if oc == 7:
                P.dma("sync", S["gamb"][tg], gamb_t[:].rearrange("p a b -> p (a b)"), reads=["gamb_t"], writes=[("gamb", tg)], sem="gamb_t")
                P.dma("sync", S["bon"][tg], bon_t[:].rearrange("p a b -> p (a b)"), reads=["bon_t"], writes=[("bon", tg)], sem="bon_t")
            yield

        def chain(ti, oc, q, d, z):
            AR, KT, BT, Vb = ARq[q][d], KTq[q][d], BTq[q][d], Vbq[z]
            ARn, KTn, BTn, Vbn = f"AR{q}{d}", f"KT{q}{d}", f"BT{q}{d}", f"Vb{z}"
            iv, fn = inv[d], fin[q][d]
            IR = lambda nm: f"i{d}_{nm}"
            FR = lambda nm: f"f{q}{d}_{nm}"
            mS, mC = (0, 2) if d == 0 else (2, 0)
            mSI = masks[:, mS:mS + 2, :].rearrange("p a b -> p (a b)").unsqueeze(1).broadcast_to([128, 4, 256])
            mCb = masks[:, mC, :].unsqueeze(1).broadcast_to([128, 4, 128])
            idb = identb[:].unsqueeze(1).broadcast_to([128, 4, 128])
            for src, srcn, dst, dstn in ((AR[:, :, 0, :], ARn, iv["Atok"], IR("Atok")), (BT[:], BTn, iv["Btok"], IR("Btok")),
                                         (KT[:], KTn, fn["Ktok"], FR("Ktok"))):
                j = nxt("pb")
                pbt = pb[j][:].bitcast(BF16)
                for u in range(4):
                    P.tr(pbt[:, u * 128:(u + 1) * 128], src[:, u, :], identb[:], [srcn, "identb"], [f"pb{j}"])
                P.cp("scalar", dst[:].rearrange("p u x -> p (u x)"), pbt[:, 0:512], [f"pb{j}"], [dstn])
            mSb = masks[:, mS, :].unsqueeze(1).broadcast_to([128, 4, 128])
            mIb = masks[:, mS + 1, :].unsqueeze(1).broadcast_to([128, 4, 128])

            def two_bank(mm_fn):
                j0, j1 = nxt("pb"), nxt("pb")
                for u in range(4):
                    mm_fn(u, pb[j0][:, u * 128:(u + 1) * 128], f"pb{j0}", pb[j1][:, u * 128:(u + 1) * 128], f"pb{j1}")
                return j0, j1

            for lhs, lhsn, dst, dstn in ((BT, BTn, iv["MQ"], IR("MQ")), (KT, KTn, fn["NP"], FR("NP"))):
                def mm_ab(u, o0, n0, o1, n1, lhs=lhs, lhsn=lhsn):
                    P.mm(o0, lhs[:, u, :], AR[:, u, 0, :], True, True, [lhsn, ARn], [n0])
                    P.mm(o1, lhs[:, u, :], AR[:, u, 1, :], True, True, [lhsn, ARn], [n1])
                j0, j1 = two_bank(mm_ab)
                P.tt("vector", dst[:, :, 0:128], u128(pb[j0][:]), mSb, ALU.mult, [f"pb{j0}", "masks"], [dstn])
                P.tt("vector", dst[:, :, 128:256], u128(pb[j1][:]), mIb, ALU.mult, [f"pb{j1}", "masks"], [dstn])
            j = nxt("pb")
            for u in range(4):
                P.mm(pb[j][:, u * 128:(u + 1) * 128], AR[:, u, 0, :], BT[:, u, :], True, True, [ARn, BTn], [f"pb{j}"])
            cur, curn, nx, nxn = iv["MWa"], IR("MWa"), iv["MWb"], IR("MWb")
            P.tt("vector", cur[:, :, 0, :], u128(pb[j][:]), mCb, ALU.mult, [f"pb{j}", "masks"], [curn])
            yield
            j = nxt("pb")
            for u in range(4):
                P.mm(pb[j][:, u * 128:(u + 1) * 128], iv["MQ"][:, u, 0:128], cur[:, u, 0, :], True, True, [IR("MQ"), curn], [f"pb{j}"])
            P.cp("scalar", nx[:, :, 0, :], u128(pb[j][:]), [f"pb{j}"], [nxn])
            P.tt("gpsimd", nx[:, :, 1, :], cur[:, :, 0, :], idb, ALU.add, [curn, "identb"], [nxn])
            j = nxt("pb")
            for u in range(4):
                P.mm(pb[j][:, u * 128:(u + 1) * 128], cur[:, u, 0, :], iv["MQ"][:, u, 0:128], True, True, [IR("MQ"), curn], [f"pb{j}"])
            curT, curTn, nxT, nxTn = iv["MTa"], IR("MTa"), iv["MTb"], IR("MTb")
            P.cp("scalar", curT[:], u128(pb[j][:]), [f"pb{j}"], [curTn])
            cur, curn, nx, nxn = nx, nxn, cur, curn
            yield
            for lev in range(1, 5):
                def mm_lev(u, o0, n0, o1, n1, cur=cur, curn=curn, curT=curT, curTn=curTn):
                    P.mm(o0, curT[:, u, :], cur[:, u, 0, :], True, True, [curTn, curn], [n0])
                    P.mm(o1, curT[:, u, :], cur[:, u, 1, :], True, True, [curTn, curn], [n1])
                j0, j1 = two_bank(mm_lev)
                P.cp("scalar", nx[:, :, 0, :], u128(pb[j0][:]), [f"pb{j0}"], [nxn])
                P.tt("vector", nx[:, :, 1, :], u128(pb[j1][:]), cur[:, :, 1, :], ALU.add, [f"pb{j1}", curn], [nxn])
                j = nxt("pb")
                for u in range(4):
                    P.mm(pb[j][:, u * 128:(u + 1) * 128], cur[:, u, 0, :], curT[:, u, :], True, True, [curn, curTn], [f"pb{j}"])
                P.cp("scalar", nxT[:], u128(pb[j][:]), [f"pb{j}"], [nxTn])
                cur, curn, nx, nxn = nx, nxn, cur, curn
                curT, curTn, nxT, nxTn = nxT, nxTn, curT, curTn
                yield
            j = nxt("pb")
            for u in range(4):
                P.mm(pb[j][:, u * 128:(u + 1) * 128], curT[:, u, :], cur[:, u, 1, :], True, True, [curTn, curn], [f"pb{j}"])
            P.tt("vector", nx[:, :, 1, :], u128(pb[j][:]), cur[:, :, 1, :], ALU.add, [f"pb{j}", curn], [nxn])
            W6, W6n = nx, nxn
            j = nxt("pb")
            for u in range(4):
                P.mm(pb[j][:, u * 64:(u + 1) * 64], fn["NP"][:, u, 0:128], Vb[:, u, :], True, True, [FR("NP"), Vbn], [f"pb{j}"])
            P.cp("scalar", fn["NVb"][:].rearrange("p u x -> p (u x)"), pb[j][:, 0:256], [f"pb{j}"], [FR("NVb")])
            yield

            def mm_d(u, o0, n0, o1, n1):
                P.mm(o0, W6[:, u, 1, :], iv["MQ"][:, u, 128:256], True, True, [W6n, IR("MQ")], [n0])
                P.mm(o1, W6[:, u, 1, :], iv["Btok"][:, u, :], True, True, [W6n, IR("Btok")], [n1])
            j0, j1 = two_bank(mm_d)
            P.cp("scalar", fn["XW"][:, :, 0:128], u128(pb[j0][:]), [f"pb{j0}"], [FR("XW")])
            P.cp("vector", fn["XW"][:, :, 128:256], u128(pb[j1][:]), [f"pb{j1}"], [FR("XW")])
            yield

            def mm_f(u, o0, n0, o1, n1):
                P.mm(o0, iv["Atok"][:, u, :], fn["XW"][:, u, 0:128], True, True, [IR("Atok"), FR("XW")], [n0])
                P.mm(o1, iv["Atok"][:, u, :], fn["XW"][:, u, 128:256], True, True, [IR("Atok"), FR("XW")], [n1])
            j0, j1 = two_bank(mm_f)
            P.tt("vector", fn["GY"][:], u128(pb[j0][:]), AR[:, :, 1, :], ALU.add, [f"pb{j0}", ARn], [FR("GY")])
            P.tt("vector", fn["GS"][:], u128(pb[j1][:]), idb, ALU.add, [f"pb{j1}", "identb"], [FR("GS")])
            yield

        def finish(ti, oc, q, z):
            isctx, idx = RW_ORDER1[ti]
            tg = 16 if isctx else idx
            sf, sb_ = fin[q]
            F0 = lambda nm: f"f{q}0_{nm}"
            F1 = lambda nm: f"f{q}1_{nm}"
            Vb, Vbn, gam = Vbq[z], f"Vb{z}", gamq[z]
            SFR = ("Sf", oc)
            for u in range(4):
                yo = pf[:, u * 64:(u + 1) * 64]
                P.mm(yo, sf["NP"][:, u, 128:256], Vb[:, u, :], True, False, [F0("NP"), Vbn], ["pf"])
                P.mm(yo, sf["XW"][:, u, 0:128], sf["NVb"][:, u, :], False, False, [F0("XW"), F0("NVb")], ["pf"])
                P.mm(yo, sb_["NP"][:, u, 128:256], Vb[:, u, :], False, False, [F1("NP"), Vbn], ["pf"])
                P.mm(yo, sb_["XW"][:, u, 0:128], sb_["NVb"][:, u, :], False, False, [F1("XW"), F1("NVb")], ["pf"])
                P.mm(yo, sf["GY"][:, u, :], Sf[:, oc, :], False, True, [F0("GY"), SFR], ["pf"])
                so = pf[:, 256:320]
                P.mm(so, sf["Ktok"][:, u, :], Vb[:, u, :], True, False, [F0("Ktok"), Vbn], ["pf"])
                P.mm(so, sf["XW"][:, u, 128:256], sf["NVb"][:, u, :], False, False, [F0("XW"), F0("NVb")], ["pf"])
                P.mm(so, sf["GS"][:, u, :], Sf[:, oc, :], False, True, [F0("GS"), SFR], ["pf"])
                P.ts("vector", Sf[:, oc, :], so, gam[0][:, u:u + 1], None, ALU.mult, None, ["pf", f"gam{z}0"], [SFR])
                yield
            P.cp("vector", YPs[:].rearrange("p u x -> p (u x)"), pf[:, 0:256], ["pf"], ["YPs"])
            P.dma("sync", S["yp"][tg, oc], YPs[:].rearrange("p u x -> p (u x)"), reads=["YPs"], writes=[("yp", tg, oc)], sem="YPs")
            j = nxt("pb")
            for u in range(4):
                so = pb[j][:, u * 64:(u + 1) * 64]
                P.mm(so, sb_["Ktok"][:, u, :], Vb[:, u, :], True, False, [F1("Ktok"), Vbn], [f"pb{j}"])
                P.mm(so, sb_["XW"][:, u, 128:256], sb_["NVb"][:, u, :], False, True, [F1("XW"), F1("NVb")], [f"pb{j}"])
            P.cp("scalar", SAs[:].rearrange("p u x -> p (u x)"), pb[j][:, 0:256], [f"pb{j}"], ["SAs"])
            P.dma("sync", S["sadd"][tg, oc], SAs[:].rearrange("p u x -> p (u x)"), reads=["SAs"], writes=[("sadd", tg, oc)], sem="SAs")
            P.dma("sync", S["gyb"][tg, oc], sb_["GY"][:].rearrange("p u x -> p (u x)"), reads=[F1("GY")], writes=[("gyb", tg, oc)], sem=F1("GY"))
            P.dma("sync", S["gsb"][tg, oc], sb_["GS"][:].rearrange("p u x -> p (u x)"), reads=[F1("GS")], writes=[("gsb", tg, oc)], sem=F1("GS"))
            yield

        NT = len(RW_ORDER1)
        NJ = NT * 8
        done = {"prep": set(), "c0": set(), "c1": set(), "fin": set(), "tprep": set()}

        def stream_P():
            for ti in range(NT):
                yield ("tprep", ti, lambda ti=ti: (ti == 0 or ("prep", (ti - 1) * 8 + 7) in donef), lambda ti=ti: tprep(ti))
                for oc in range(8):
                    k = ti * 8 + oc
                    yield ("prep", k, lambda k=k: ((k < 2 or (("c0", k - 2) in donef and ("c1", k - 2) in donef)) and (k < 3 or ("fin", k - 3) in donef)),
                           lambda ti=ti, oc=oc, k=k: prep(ti, oc, k % 2, k % 3))

        def stream_C(d):
            for k in range(NJ):
                ti, oc = divmod(k, 8)
                yield (f"c{d}", k, lambda k=k: (("prep", k) in donef and (k < 2 or ("fin", k - 2) in donef)),
                       lambda ti=ti, oc=oc, k=k: chain(ti, oc, k % 2, d, k % 3))

        def stream_F():
            for k in range(NJ):
                ti, oc = divmod(k, 8)
                yield ("fin", k, lambda k=k: (("c0", k) in donef and ("c1", k) in donef),
                       lambda ti=ti, oc=oc, k=k: finish(ti, oc, k % 2, k % 3))

        donef = set()
        load_hh(0)
        streams = [stream_C(0), stream_C(1), stream_F(), stream_P()]
        cur = [None] * 4
        pend = [None] * 4
        alive = [True] * 4
        while any(alive):
            progressed = False
            for si in range(4):
                if not alive[si]:
                    continue
                if cur[si] is None:
                    if pend[si] is None:
                        try:
                            pend[si] = next(streams[si])
                        except StopIteration:
                            alive[si] = False
                            continue
                    kind, k, ready, mk = pend[si]
                    if not ready():
                        continue
                    cur[si] = (kind, k, mk())
                    pend[si] = None
                kind, k, gen = cur[si]
                try:
                    next(gen)
                    progressed = True
                except StopIteration:
                    donef.add((kind, k))
                    cur[si] = None
                    progressed = True
            assert progressed or not any(alive), "scheduler stuck"


def stage_rwkv1a(P, io, G, hp, S):
    vec, masks, identb, identf, bones, onesb, rmask = (G[k] for k in ("vec", "masks", "identb", "identf", "bones", "onesb", "rmask"))
    with P.phase("rwkv1a"):
        wr = P.sb([128, 8, 1024], BF16)
        wk = P.sb([128, 8, 1024], BF16)
        wv = P.sb([128, 8, 1024], BF16)
        for w, nm in ((wr, "rwkv_wr"), (wk, "rwkv_wk"), (wv, "rwkv_wv")):
            P.dma("gpsimd", w[:], fm(io[nm]), writes=[nm], sem=nm)
        lw1 = P.sb([128, 8, 128], BF16)
        la1 = P.sb([128, 8, 128], BF16)
        g1 = P.sb([128, 8, 128], BF16)
        for d in range(2):
            P.dma("gpsimd", lw1[:, :, d * 64:(d + 1) * 64], io["rwkv_w1"][d].rearrange("(c p) j -> p c j", p=128), writes=["lw1"], sem=f"lw1{d}")
            P.dma("gpsimd", la1[:, :, d * 64:(d + 1) * 64], io["rwkv_a1"][d].rearrange("(c p) j -> p c j", p=128), writes=["la1"], sem=f"la1{d}")
        P.dma("gpsimd", g1[:], io["rwkv_g1"].rearrange("(c p) j -> p c j", p=128), writes=["g1"], sem="g1")
        w2s = P.sb([128, 1024], BF16)
        a2s = P.sb([128, 1024], BF16)
        g2 = P.sb([128, 1024], BF16)
        P.dma("gpsimd", w2s[:], io["rwkv_w2"].rearrange("d j f -> (d j) f"), writes=["w2s"], sem="w2s")
        P.dma("gpsimd", a2s[:], io["rwkv_a2"].rearrange("d j f -> (d j) f"), writes=["a2s"], sem="a2s")
        P.dma("gpsimd", g2[:], io["rwkv_g2"], writes=["g2"], sem="g2")

        hh = P.sb([128, 8, 384], F32)
        xx = P.sb([128, 8, 256], F32)
        xr = P.sb([128, 8, 256], BF16)
        xk = P.sb([128, 8, 256], BF16)
        xv = P.sb([128, 8, 256], BF16)
        xrot = P.sb([128, 8, 256], BF16)
        lwt = P.sb([128, 256], BF16)
        lat = P.sb([128, 256], BF16)
        sg = P.sb([128, 256], BF16)
        NSET = 3
        bufs = []
        for w_ in range(NSET):
            B_ = {"t": {}}
            for nm in ("r", "k", "sw0", "sw1", "ag0", "ag1", "kq", "lnv", "rs", "kkn", "fac", "kd0", "kd1", "b0", "b1",
                       "L", "Lx", "Lb", "E1", "E2", "E3", "ks"):
                B_["t"][nm] = P.sb([128, 256], F32, f"t{w_}_" + nm)
            B_["sqb"] = P.sb([128, 256], BF16)
            B_["RK"] = P.sb([128, 4, 2, 64], BF16)
            B_["VTbd"] = P.sb([128, 4, 128], F32)
            B_["GTbd"] = P.sb([128, 4, 128], F32)
            B_["Vf"] = P.sb([128, 4, 64], F32)
            B_["Gf"] = P.sb([128, 4, 64], F32)
            B_["ops"] = P.sb([128, 2, 4, 256], BF16)
            B_["vb"] = P.sb([128, 4, 64], BF16)
            B_["gam"] = P.sb([128, 2, 4], F32)
            bufs.append(B_)
            P.memset("gpsimd", B_["RK"][:], 0.0, [("RK", w_)])
            P.memset("gpsimd", B_["VTbd"][:], 0.0, [("VTbd", w_)])
            P.memset("gpsimd", B_["GTbd"][:], 0.0, [("GTbd", w_)])
        P.ns_set = frozenset(["r", "k", "sw0", "sw1", "ag0", "ag1", "kq", "lnv", "rs", "kkn", "fac", "kd0", "kd1", "b0", "b1",
                              "L", "Lx", "Lb", "E1", "E2", "E3", "ks", "sqb", "RK", "VTbd", "GTbd", "Vf", "Gf", "ops_st", "vb_st", "gam_st"])
        gamb_t = P.sb([128, 8, 4], F32)
        bon_t = P.sb([128, 8, 4], F32)
        ppt = [P.ps([128, 512], F32) for _ in range(4)]
        pp = [t_[:, 0:256] for t_ in ppt]
        pb = [P.ps([128, 512], F32) for _ in range(4)]
        cnt = {"pp": 0, "pb": 0}
        nmod = {"pp": 4, "pb": 4}

        def nxt(kind):
            i = cnt[kind] % nmod[kind]
            cnt[kind] += 1
            return i

        def v3(ap):
            return ap.rearrange("p (u s) -> p u s", s=64)

        def u128(ap):
            return ap.rearrange("p (u x) -> p u x", x=128)

        def load_hh(ti):
            isctx, idx = RW_ORDER1[ti]
            off = 4288 if isctx else 64 + 256 * idx
            P.dma("sync", hh[:], fm(hp[:, off - 64: off + 320]), writes=["hh"], sem="hh")

        def proj8(w_cols_fn, xb, bn, extra_r):
            i = nxt("pp")
            for c in range(8):
                P.mm(pp[i], w_cols_fn(c), xb[:, c, :], c == 0, c == 7, [(bn, c)] + extra_r, [f"pp{i}"])
            return i

        def tprep(ti):
            isctx, idx = RW_ORDER1[ti]
            hc = hh[:, :, 64:320]
            XXW = [("xx", c) for c in range(8)]
            if not isctx:
                h4 = hh[:, :, 64:320].rearrange("p c (r w) -> p c r w", w=64)
                x4 = xx[:].rearrange("p c (r w) -> p c r w", w=64)
                P.tt("vector", x4[:, 0:2, :, 1:64], h4[:, 0:2, :, 0:63], h4[:, 0:2, :, 1:64], ALU.subtract, ["hh"], XXW[0:2])
                P.ts("gpsimd", x4[:, 0:2, :, 0:1], h4[:, 0:2, :, 0:1], -1.0, 0.0, ALU.mult, ALU.add, ["hh"], [("xxe", 0)])
                P.tt("vector", x4[:, 2:4, :, 0:63], h4[:, 2:4, :, 1:64], h4[:, 2:4, :, 0:63], ALU.subtract, ["hh"], XXW[2:4])
                P.ts("gpsimd", x4[:, 2:4, :, 63:64], h4[:, 2:4, :, 63:64], -1.0, 0.0, ALU.mult, ALU.add, ["hh"], [("xxe", 1)])
                P.tt("gpsimd", xx[:, 4:6, :], hh[:, 4:6, 0:256], hh[:, 4:6, 64:320], ALU.subtract, ["hh"], XXW[4:6])
                P.tt("gpsimd", xx[:, 6:8, :], hh[:, 6:8, 128:384], hh[:, 6:8, 64:320], ALU.subtract, ["hh"], XXW[6:8])
            else:
                P.tt("vector", xx[:, 0:4, :], hh[:, 0:4, 63:319], hh[:, 0:4, 64:320], ALU.subtract, ["hh"], XXW[0:4] + [("xxe", 0)])
                P.tt("gpsimd", xx[:, 4:8, :], hh[:, 4:8, 65:321], hh[:, 4:8, 64:320], ALU.subtract, ["hh"], XXW[4:8] + [("xxe", 1)])
            yield

            def mk_xj(j, buf, bn):
                for c in range(8):
                    P.stt(buf[:, c, :], xx[:, c, :], vec[:, 5 + j, c:c + 1], hc[:, c, :], ALU.mult, ALU.add,
                          [("xx", c), ("xxe", 0), ("xxe", 1), "hh", "vec"], [(bn, c)])

            mk_xj(1, xrot, "xrot")
            yield
            i = proj8(lambda c: lw1[:, c, :], xrot, "xrot", ["lw1"])
            P.act(lwt[:], pp[i], AF.Tanh, [f"pp{i}"], ["lwt"])
            yield
            mk_xj(4, xrot, "xrot")
            yield
            i = proj8(lambda c: la1[:, c, :], xrot, "xrot", ["la1"])
            P.cp("scalar", lat[:], pp[i], [f"pp{i}"], ["lat"])
            yield
            mk_xj(5, xrot, "xrot")
            yield
            i = proj8(lambda c: g1[:, c, :], xrot, "xrot", ["g1"])
            P.act(sg[:], pp[i], AF.Sigmoid, [f"pp{i}"], ["sg"])
            yield
            mk_xj(0, xr, "xr")
            yield
            mk_xj(2, xk, "xk")
            yield
            mk_xj(3, xv, "xv")
            if ti + 1 < len(RW_ORDER1):
                load_hh(ti + 1)
            yield

        def prep(ti, oc, w):
            isctx, idx = RW_ORDER1[ti]
            tg = 16 if isctx else idx
            cs = slice(oc * 128, (oc + 1) * 128)
            B_ = bufs[w]
            t, sqb, RK, VTbd, GTbd, Vf, Gf = B_["t"], B_["sqb"], B_["RK"], B_["VTbd"], B_["GTbd"], B_["Vf"], B_["Gf"]
            ops_st, vb_st, gam_st = B_["ops"], B_["vb"], B_["gam"]
            i = proj8(lambda c: wr[:, c, cs], xr, "xr", ["rwkv_wr"])
            P.cp("scalar", t["r"][:], pp[i], [f"pp{i}"], ["r"])
            i = proj8(lambda c: wk[:, c, cs], xk, "xk", ["rwkv_wk"])
            P.cp("scalar", t["k"][:], pp[i], [f"pp{i}"], ["k"])
            i = proj8(lambda c: wv[:, c, cs], xv, "xv", ["rwkv_wv"])
            vt4 = VTbd[:].rearrange("p u (h s) -> p u h s", h=2)
            for h2 in range(2):
                sl = slice(h2 * 64, (h2 + 1) * 64)
                P.cp("scalar", vt4[sl, :, h2, :], v3(pp[i][sl, :]), [f"pp{i}"], ["VTbd"])
            i = nxt("pp")
            P.mm(pp[i], g2[:, cs], sg[:], True, True, ["g2", "sg"], [f"pp{i}"])
            gt4 = GTbd[:].rearrange("p u (h s) -> p u h s", h=2)
            for h2 in range(2):
                sl = slice(h2 * 64, (h2 + 1) * 64)
                P.cp("scalar", gt4[sl, :, h2, :], v3(pp[i][sl, :]), [f"pp{i}"], ["GTbd"])
            yield
            j = nxt("pb")
            for u in range(4):
                P.tr(pb[j][:, u * 128:(u + 1) * 128], VTbd[:, u, :], identf[:], ["VTbd", "identf"], [f"pb{j}"])
            pv = u128(pb[j][:])
            for h2 in range(2):
                sl = slice(h2 * 64, (h2 + 1) * 64)
                P.cp("scalar", Vf[sl, :, :], pv[sl, :, h2 * 64:(h2 + 1) * 64], [f"pb{j}"], ["Vf"])
            P.cp("gpsimd", vb_st[:], Vf[:], ["Vf"], ["vb_st"])
            P.dma("sync", S["vst"][tg, oc].rearrange("p (u s) -> p u s", s=64), Vf[:], reads=["Vf"], writes=[("vst", tg, oc)], sem="Vf")
            j = nxt("pb")
            for u in range(4):
                P.tr(pb[j][:, u * 128:(u + 1) * 128], GTbd[:, u, :], identf[:], ["GTbd", "identf"], [f"pb{j}"])
            pv = u128(pb[j][:])
            for h2 in range(2):
                sl = slice(h2 * 64, (h2 + 1) * 64)
                P.cp("scalar", Gf[sl, :, :], pv[sl, :, h2 * 64:(h2 + 1) * 64], [f"pb{j}"], ["Gf"])
            P.dma("sync", S["gst"][tg, oc].rearrange("p (u s) -> p u s", s=64), Gf[:], reads=["Gf"], writes=[("gst", tg, oc)], sem="Gf")
            yield
            for d in range(2):
                dl = slice(d * 64, (d + 1) * 64)
                i = nxt("pp")
                P.mm(pp[i], w2s[dl, cs], lwt[dl, :], True, True, ["w2s", "lwt"], [f"pp{i}"])
                P.act(t[f"sw{d}"][:], pp[i], AF.Sigmoid, [f"pp{i}", "vec"], [f"sw{d}"], bias=vec[:, 11 + d, oc:oc + 1])
                i = nxt("pp")
                P.mm(pp[i], a2s[dl, cs], lat[dl, :], True, True, ["a2s", "lat"], [f"pp{i}"])
                P.act(t[f"ag{d}"][:], pp[i], AF.Sigmoid, [f"pp{i}", "vec"], [f"ag{d}"], bias=vec[:, 13 + d, oc:oc + 1])
            yield
            P.ts("vector", t["kq"][:], t["k"][:], vec[:, 15, oc:oc + 1], None, ALU.mult, None, ["k", "vec"], ["kq"])
            P.act(sqb[:], t["kq"][:], AF.Square, ["kq"], ["sqb"])
            i = nxt("pp")
            P.mm(pp[i], bones[:], sqb[:], True, True, ["bones", "sqb"], [f"pp{i}"])
            P.act(t["lnv"][:], pp[i], AF.Ln, [f"pp{i}"], ["lnv"], bias=1e-12)
            P.act(t["rs"][:], t["lnv"][:], AF.Exp, ["lnv"], ["rs"], scale=-0.5)
            P.tt("vector", t["kkn"][:], t["kq"][:], t["rs"][:], ALU.mult, ["kq", "rs"], ["kkn"])
            for d in range(2):
                sw, ag, kd, bb = t[f"sw{d}"], t[f"ag{d}"], t[f"kd{d}"], t[f"b{d}"]
                EE = "vector"
                P.ts(EE, t["fac"][:], ag[:], vec[:, 16, oc:oc + 1], vec[:, 17, oc:oc + 1], ALU.mult, ALU.add, [f"ag{d}", "vec"], ["fac"])
                P.tt(EE, kd[:], t["k"][:], t["fac"][:], ALU.mult, ["k", "fac"], [f"kd{d}"])
                P.tt(EE, bb[:], t["kkn"][:], ag[:], ALU.mult, ["kkn", f"ag{d}"], [f"b{d}"])
                P.op("vector", lambda e, sw=sw: e.tensor_tensor_scan(out=t["L"][:], data0=rmask[:], data1=sw[:], initial=0.0,
                                                                      op0=ALU.mult, op1=ALU.add), [f"sw{d}", "rmask"], ["L"])
                L3 = v3(t["L"][:])
                if d == 0:
                    P.tt(EE, t["Lx"][:], t["L"][:], sw[:], ALU.subtract, ["L", f"sw{d}"], ["Lx"])
                    Li, Lin = t["L"], "L"
                else:
                    P.tt(EE, v3(t["Lx"][:]), L3[:, :, 63:64].broadcast_to([128, 4, 64]), L3, ALU.subtract, ["L"], ["Lx"])
                    P.tt(EE, t["Lb"][:], t["Lx"][:], sw[:], ALU.add, ["Lx", f"sw{d}"], ["Lb"])
                    Li, Lin = t["Lb"], "Lb"
                P.act(t["E1"][:], Li[:], AF.Exp, [Lin], ["E1"], scale=-C0)
                P.act(t["E3"][:], Li[:], AF.Exp, [Lin], ["E3"], scale=C0)
                P.act(t["E2"][:], t["Lx"][:], AF.Exp, ["Lx"], ["E2"], scale=-C0)
                P.stt(ops_st[:, d, 0, :], t["kkn"][:], -1.0, t["E2"][:], ALU.mult, ALU.mult, ["kkn", "E2"], ["ops_st"])
                P.tt("vector", ops_st[:, d, 1, :], t["r"][:], t["E1"][:], ALU.mult, ["r", "E1"], ["ops_st"])
                P.tt(EE, ops_st[:, d, 2, :], kd[:], t["E3"][:], ALU.mult, [f"kd{d}", "E3"], ["ops_st"])
                P.tt(EE, ops_st[:, d, 3, :], bb[:], t["E3"][:], ALU.mult, [f"b{d}", "E3"], ["ops_st"])
                E13 = v3(t["E1"][:])
                gsrc = E13[:, :, 63] if d == 0 else E13[:, :, 0]
                P.cp("vector", gam_st[:, d, :], gsrc, ["E1"], ["gam_st"])
                if d == 1:
                    P.cp("gpsimd", gamb_t[:, oc, :], gam_st[:, 1, :], ["gam_st"], ["gamb_t"])
                yield
            P.tt("vector", t["ks"][:], t["kd0"][:], t["kd1"][:], ALU.add, ["kd0", "kd1"], ["ks"])
            for h2 in range(2):
                sl = slice(h2 * 64, (h2 + 1) * 64)
                P.stt(RK[sl, :, h2, :], v3(t["r"][sl, :]), vec[sl, 18, oc:oc + 1], v3(t["ks"][sl, :]), ALU.mult, ALU.mult, ["r", "ks", "vec"], ["RK"])
            i = nxt("pp")
            for u in range(4):
                P.mm(pp[i][:, u:u + 1], RK[:, u, :, :].rearrange("p h s -> p (h s)"), onesb[:, 0:1], True, True, ["RK", "onesb"], [f"pp{i}"])
            P.cp("scalar", bon_t[:, oc, :], pp[i][:, 0:4], [f"pp{i}"], ["bon_t"])
            P.dma("sync", S["ops"][tg, oc], ops_st[:].rearrange("p d x n -> p (d x n)"), reads=["ops_st"], writes=[("ops", tg, oc)], sem="ops_st")
            P.dma("sync", S["vb"][tg, oc], vb_st[:].rearrange("p u s -> p (u s)"), reads=["vb_st"], writes=[("vb", tg, oc)], sem="vb_st")
            P.dma("sync", S["gam"][tg, oc], gam_st[:].rearrange("p d u -> p (d u)"), reads=["gam_st"], writes=[("gam", tg, oc)], sem="gam_st")
            yield


        NT = len(RW_ORDER1)
        load_hh(0)
        for ti in range(NT):
            isctx, idx = RW_ORDER1[ti]
            tg = 16 if isctx else idx
            for _ in tprep(ti):
                pass
            jobs = [(oc % NSET, prep(ti, oc, oc % NSET)) for oc in range(8)]
            active = []
            since = 99
            while jobs or active:
                if jobs and len(active) < NSET and (since >= 3 or not active):
                    active.append(jobs.pop(0))
                    since = 0
                since += 1
                for item in list(active):
                    P.ns = item[0]
                    try:
                        next(item[1])
                    except StopIteration:
                        active.remove(item)
                    P.ns = None
            P.dma("sync", S["gamb"][tg], gamb_t[:].rearrange("p a b -> p (a b)"), reads=["gamb_t"], writes=[("gamb", tg)], sem="gamb_t")
            P.dma("sync", S["bon"][tg], bon_t[:].rearrange("p a b -> p (a b)"), reads=["bon_t"], writes=[("bon", tg)], sem="bon_t")
        P.ns_set = frozenset()


def stage_rwkv1b(P, io, G, S):
    vec, masks, identb, identf, bones, onesb, rmask = (G[k] for k in ("vec", "masks", "identb", "identf", "bones", "onesb", "rmask"))
    with P.phase("rwkv1b"):
        YPs = P.sb([128, 4, 64], F32)
        SAs = P.sb([128, 4, 64], F32)
        Sf = P.sb([128, 8, 64], BF16)
        ARq = [[P.sb([128, 4, 2, 128], BF16, f"AR{q}{d}") for d in range(2)] for q in range(3)]
        KTq = [[P.sb([128, 4, 128], BF16, f"KT{q}{d}") for d in range(2)] for q in range(3)]
        BTq = [[P.sb([128, 4, 128], BF16, f"BT{q}{d}") for d in range(2)] for q in range(3)]
        stg = [P.sb([128, 2, 4, 256], BF16, f"stg{q}") for q in range(3)]
        Vbq = [P.sb([128, 4, 64], BF16, f"Vb{q}") for q in range(4)]
        gamq = [P.sb([128, 2, 4], F32, f"gam{q}") for q in range(4)]
        inv2 = []
        for q in range(2):
            row = []
            for d in range(2):
                st = {}
                for nm, shp in (("Atok", [128, 4, 128]), ("Btok", [128, 4, 128]), ("MQ", [128, 4, 256]), ("MWa", [128, 4, 2, 128]),
                                ("MWb", [128, 4, 2, 128]), ("MTa", [128, 4, 128]), ("MTb", [128, 4, 128])):
                    st[nm] = P.sb(shp, BF16, f"i{q}{d}_{nm}")
                row.append(st)
            inv2.append(row)
        fin = []
        for q in range(2):
            row = []
            for d in range(2):
                st = {}
                for nm, shp in (("Ktok", [128, 4, 128]), ("NP", [128, 4, 256]), ("XW", [128, 4, 256]), ("NVb", [128, 4, 64]),
                                ("GY", [128, 4, 128]), ("GS", [128, 4, 128])):
                    st[nm] = P.sb(shp, BF16, f"f{q}{d}_{nm}")
                row.append(st)
            fin.append(row)
        pf = P.ps([128, 512], F32)
        pb = [P.ps([128, 512], F32) for _ in range(7)]
        cnt = {"pb": 0}
        nmod = {"pb": 7}

        def nxt(kind):
            i = cnt[kind] % nmod[kind]
            cnt[kind] += 1
            return i

        for q in range(3):
            for d in range(2):
                P.memset("gpsimd", ARq[q][d][:], 0.0, [f"AR{q}{d}"])
                P.memset("gpsimd", KTq[q][d][:], 0.0, [f"KT{q}{d}"])
                P.memset("gpsimd", BTq[q][d][:], 0.0, [f"BT{q}{d}"])
        P.memset("gpsimd", Sf[:], 0.0, [("Sf", p) for p in range(8)])

        def v3(ap):
            return ap.rearrange("p (u s) -> p u s", s=64)

        def u128(ap):
            return ap.rearrange("p (u x) -> p u x", x=128)

        def loadjob(ti, oc, a, z):
            isctx, idx = RW_ORDER1[ti]
            tg = 16 if isctx else idx
            sg_ = stg[a]
            P.dma("sync", sg_[:].rearrange("p d x n -> p (d x n)"), S["ops"][tg, oc], writes=[f"stg{a}"], sem=f"stg{a}")
            P.dma("sync", Vbq[z][:].rearrange("p u s -> p (u s)"), S["vb"][tg, oc], writes=[f"Vb{z}"], sem=f"Vb{z}")
            P.dma("sync", gamq[z][:].rearrange("p d u -> p (d u)"), S["gam"][tg, oc], writes=[f"gam{z}"], sem=f"gam{z}")
            yield
            for d in range(2):
                ar5 = ARq[a][d][:].rearrange("p u a (h s) -> p u a h s", h=2)
                kt4 = KTq[a][d][:].rearrange("p u (h s) -> p u h s", h=2)
                bt4 = BTq[a][d][:].rearrange("p u (h s) -> p u h s", h=2)
                for h2 in range(2):
                    sl = slice(h2 * 64, (h2 + 1) * 64)
                    P.cp("gpsimd", ar5[sl, :, 0, h2, :], v3(sg_[sl, d, 0, :]), [f"stg{a}"], [f"AR{a}{d}"])
                    P.cp("gpsimd", ar5[sl, :, 1, h2, :], v3(sg_[sl, d, 1, :]), [f"stg{a}"], [f"AR{a}{d}"])
                    P.cp("gpsimd", kt4[sl, :, h2, :], v3(sg_[sl, d, 2, :]), [f"stg{a}"], [f"KT{a}{d}"])
                    P.cp("gpsimd", bt4[sl, :, h2, :], v3(sg_[sl, d, 3, :]), [f"stg{a}"], [f"BT{a}{d}"])
                    yield

        def chain(ti, oc, q, d, z, a):
            AR, KT, BT, Vb = ARq[a][d], KTq[a][d], BTq[a][d], Vbq[z]
            ARn, KTn, BTn, Vbn = f"AR{a}{d}", f"KT{a}{d}", f"BT{a}{d}", f"Vb{z}"
            iv, fn = inv2[q][d], fin[q][d]
            IR = lambda nm: f"i{q}{d}_{nm}"
            FR = lambda nm: f"f{q}{d}_{nm}"
            mS, mC = (0, 2) if d == 0 else (2, 0)
            mSI = masks[:, mS:mS + 2, :].rearrange("p a b -> p (a b)").unsqueeze(1).broadcast_to([128, 4, 256])
            mCb = masks[:, mC, :].unsqueeze(1).broadcast_to([128, 4, 128])
            idb = identb[:].unsqueeze(1).broadcast_to([128, 4, 128])
            for src, srcn, dst, dstn in ((AR[:, :, 0, :], ARn, iv["Atok"], IR("Atok")), (BT[:], BTn, iv["Btok"], IR("Btok")),
                                         (KT[:], KTn, fn["Ktok"], FR("Ktok"))):
                j = nxt("pb")
                pbt = pb[j][:].bitcast(BF16)
                for u in range(4):
                    P.tr(pbt[:, u * 128:(u + 1) * 128], src[:, u, :], identb[:], [srcn, "identb"], [f"pb{j}"])
                P.cp("scalar", dst[:].rearrange("p u x -> p (u x)"), pbt[:, 0:512], [f"pb{j}"], [dstn])
            mSb = masks[:, mS, :].unsqueeze(1).broadcast_to([128, 4, 128])
            mIb = masks[:, mS + 1, :].unsqueeze(1).broadcast_to([128, 4, 128])

            def two_bank(mm_fn):
                j0, j1 = nxt("pb"), nxt("pb")
                for u in range(4):
                    mm_fn(u, pb[j0][:, u * 128:(u + 1) * 128], f"pb{j0}", pb[j1][:, u * 128:(u + 1) * 128], f"pb{j1}")
                return j0, j1

            for lhs, lhsn, dst, dstn in ((BT, BTn, iv["MQ"], IR("MQ")), (KT, KTn, fn["NP"], FR("NP"))):
                def mm_ab(u, o0, n0, o1, n1, lhs=lhs, lhsn=lhsn):
                    P.mm(o0, lhs[:, u, :], AR[:, u, 0, :], True, True, [lhsn, ARn], [n0])
                    P.mm(o1, lhs[:, u, :], AR[:, u, 1, :], True, True, [lhsn, ARn], [n1])
                j0, j1 = two_bank(mm_ab)
                P.tt("vector", dst[:, :, 0:128], u128(pb[j0][:]), mSb, ALU.mult, [f"pb{j0}", "masks"], [dstn])
                P.tt("vector", dst[:, :, 128:256], u128(pb[j1][:]), mIb, ALU.mult, [f"pb{j1}", "masks"], [dstn])
            j = nxt("pb")
            for u in range(4):
                P.mm(pb[j][:, u * 128:(u + 1) * 128], AR[:, u, 0, :], BT[:, u, :], True, True, [ARn, BTn], [f"pb{j}"])
            cur, curn, nx, nxn = iv["MWa"], IR("MWa"), iv["MWb"], IR("MWb")
            P.tt("vector", cur[:, :, 0, :], u128(pb[j][:]), mCb, ALU.mult, [f"pb{j}", "masks"], [curn])
            yield
            j = nxt("pb")
            for u in range(4):
                P.mm(pb[j][:, u * 128:(u + 1) * 128], iv["MQ"][:, u, 0:128], cur[:, u, 0, :], True, True, [IR("MQ"), curn], [f"pb{j}"])
            P.cp("scalar", nx[:, :, 0, :], u128(pb[j][:]), [f"pb{j}"], [nxn])
            P.tt("gpsimd", nx[:, :, 1, :], cur[:, :, 0, :], idb, ALU.add, [curn, "identb"], [nxn])
            j = nxt("pb")
            for u in range(4):
                P.mm(pb[j][:, u * 128:(u + 1) * 128], cur[:, u, 0, :], iv["MQ"][:, u, 0:128], True, True, [IR("MQ"), curn], [f"pb{j}"])
            curT, curTn, nxT, nxTn = iv["MTa"], IR("MTa"), iv["MTb"], IR("MTb")
            P.cp("scalar", curT[:], u128(pb[j][:]), [f"pb{j}"], [curTn])
            cur, curn, nx, nxn = nx, nxn, cur, curn
            yield
            for lev in range(1, 5):
                def mm_lev(u, o0, n0, o1, n1, cur=cur, curn=curn, curT=curT, curTn=curTn):
                    P.mm(o0, curT[:, u, :], cur[:, u, 0, :], True, True, [curTn, curn], [n0])
                    P.mm(o1, curT[:, u, :], cur[:, u, 1, :], True, True, [curTn, curn], [n1])
                j0, j1 = two_bank(mm_lev)
                P.cp("scalar", nx[:, :, 0, :], u128(pb[j0][:]), [f"pb{j0}"], [nxn])
                P.tt("vector", nx[:, :, 1, :], u128(pb[j1][:]), cur[:, :, 1, :], ALU.add, [f"pb{j1}", curn], [nxn])
                j = nxt("pb")
                for u in range(4):
                    P.mm(pb[j][:, u * 128:(u + 1) * 128], cur[:, u, 0, :], curT[:, u, :], True, True, [curn, curTn], [f"pb{j}"])
                P.cp("scalar", nxT[:], u128(pb[j][:]), [f"pb{j}"], [nxTn])
                cur, curn, nx, nxn = nx, nxn, cur, curn
                curT, curTn, nxT, nxTn = nxT, nxTn, curT, curTn
                yield
            j = nxt("pb")
            for u in range(4):
                P.mm(pb[j][:, u * 128:(u + 1) * 128], curT[:, u, :], cur[:, u, 1, :], True, True, [curTn, curn], [f"pb{j}"])
            P.tt("vector", nx[:, :, 1, :], u128(pb[j][:]), cur[:, :, 1, :], ALU.add, [f"pb{j}", curn], [nxn])
            W6, W6n = nx, nxn
            j = nxt("pb")
            for u in range(4):
                P.mm(pb[j][:, u * 64:(u + 1) * 64], fn["NP"][:, u, 0:128], Vb[:, u, :], True, True, [FR("NP"), Vbn], [f"pb{j}"])
            P.cp("scalar", fn["NVb"][:].rearrange("p u x -> p (u x)"), pb[j][:, 0:256], [f"pb{j}"], [FR("NVb")])
            yield

            def mm_d(u, o0, n0, o1, n1):
                P.mm(o0, W6[:, u, 1, :], iv["MQ"][:, u, 128:256], True, True, [W6n, IR("MQ")], [n0])
                P.mm(o1, W6[:, u, 1, :], iv["Btok"][:, u, :], True, True, [W6n, IR("Btok")], [n1])
            j0, j1 = two_bank(mm_d)
            P.cp("scalar", fn["XW"][:, :, 0:128], u128(pb[j0][:]), [f"pb{j0}"], [FR("XW")])
            P.cp("vector", fn["XW"][:, :, 128:256], u128(pb[j1][:]), [f"pb{j1}"], [FR("XW")])
            yield

            def mm_f(u, o0, n0, o1, n1):
                P.mm(o0, iv["Atok"][:, u, :], fn["XW"][:, u, 0:128], True, True, [IR("Atok"), FR("XW")], [n0])
                P.mm(o1, iv["Atok"][:, u, :], fn["XW"][:, u, 128:256], True, True, [IR("Atok"), FR("XW")], [n1])
            j0, j1 = two_bank(mm_f)
            P.tt("vector", fn["GY"][:], u128(pb[j0][:]), AR[:, :, 1, :], ALU.add, [f"pb{j0}", ARn], [FR("GY")])
            P.tt("vector", fn["GS"][:], u128(pb[j1][:]), idb, ALU.add, [f"pb{j1}", "identb"], [FR("GS")])
            yield

        def finish(ti, oc, q, z):
            isctx, idx = RW_ORDER1[ti]
            tg = 16 if isctx else idx
            sf, sb_ = fin[q]
            F0 = lambda nm: f"f{q}0_{nm}"
            F1 = lambda nm: f"f{q}1_{nm}"
            Vb, Vbn, gamz = Vbq[z], f"Vb{z}", gamq[z]
            SFR = ("Sf", oc)
            for u in range(4):
                yo = pf[:, u * 64:(u + 1) * 64]
                P.mm(yo, sf["NP"][:, u, 128:256], Vb[:, u, :], True, False, [F0("NP"), Vbn], ["pf"])
                P.mm(yo, sf["XW"][:, u, 0:128], sf["NVb"][:, u, :], False, False, [F0("XW"), F0("NVb")], ["pf"])
                P.mm(yo, sb_["NP"][:, u, 128:256], Vb[:, u, :], False, False, [F1("NP"), Vbn], ["pf"])
                P.mm(yo, sb_["XW"][:, u, 0:128], sb_["NVb"][:, u, :], False, False, [F1("XW"), F1("NVb")], ["pf"])
                P.mm(yo, sf["GY"][:, u, :], Sf[:, oc, :], False, True, [F0("GY"), SFR], ["pf"])
                so = pf[:, 256:320]
                P.mm(so, sf["Ktok"][:, u, :], Vb[:, u, :], True, False, [F0("Ktok"), Vbn], ["pf"])
                P.mm(so, sf["XW"][:, u, 128:256], sf["NVb"][:, u, :], False, False, [F0("XW"), F0("NVb")], ["pf"])
                P.mm(so, sf["GS"][:, u, :], Sf[:, oc, :], False, True, [F0("GS"), SFR], ["pf"])
                P.ts("vector", Sf[:, oc, :], so, gamz[:, 0, u:u + 1], None, ALU.mult, None, ["pf", f"gam{z}"], [SFR])
                yield
            P.cp("vector", YPs[:].rearrange("p u x -> p (u x)"), pf[:, 0:256], ["pf"], ["YPs"])
            P.dma("sync", S["yp"][tg, oc], YPs[:].rearrange("p u x -> p (u x)"), reads=["YPs"], writes=[("yp", tg, oc)], sem="YPs")
            j = nxt("pb")
            for u in range(4):
                so = pb[j][:, u * 64:(u + 1) * 64]
                P.mm(so, sb_["Ktok"][:, u, :], Vb[:, u, :], True, False, [F1("Ktok"), Vbn], [f"pb{j}"])
                P.mm(so, sb_["XW"][:, u, 128:256], sb_["NVb"][:, u, :], False, True, [F1("XW"), F1("NVb")], [f"pb{j}"])
            P.cp("scalar", SAs[:].rearrange("p u x -> p (u x)"), pb[j][:, 0:256], [f"pb{j}"], ["SAs"])
            P.dma("sync", S["sadd"][tg, oc], SAs[:].rearrange("p u x -> p (u x)"), reads=["SAs"], writes=[("sadd", tg, oc)], sem="SAs")
            P.dma("sync", S["gyb"][tg, oc], sb_["GY"][:].rearrange("p u x -> p (u x)"), reads=[F1("GY")], writes=[("gyb", tg, oc)], sem=F1("GY"))
            P.dma("sync", S["gsb"][tg, oc], sb_["GS"][:].rearrange("p u x -> p (u x)"), reads=[F1("GS")], writes=[("gsb", tg, oc)], sem=F1("GS"))
            yield


        NT = len(RW_ORDER1)
        NJ = NT * 8
        donef = set()

        def stream_L():
            for k in range(NJ):
                ti, oc = divmod(k, 8)
                yield ("load", k, lambda k=k: ((k < 3 or (("c0", k - 3) in donef and ("c1", k - 3) in donef)) and (k < 4 or ("fin", k - 4) in donef)),
                       lambda ti=ti, oc=oc, k=k: loadjob(ti, oc, k % 3, k % 4))

        def stream_C(d, par):
            for k in range(par, NJ, 2):
                ti, oc = divmod(k, 8)
                yield (f"c{d}", k, lambda k=k: (("load", k) in donef and (k < 2 or ("fin", k - 2) in donef)),
                       lambda ti=ti, oc=oc, k=k: chain(ti, oc, k % 2, d, k % 4, k % 3))

        def stream_F():
            for k in range(NJ):
                ti, oc = divmod(k, 8)
                yield ("fin", k, lambda k=k: (("c0", k) in donef and ("c1", k) in donef),
                       lambda ti=ti, oc=oc, k=k: finish(ti, oc, k % 2, k % 4))

        streams = [stream_L(), stream_C(0, 0), stream_C(1, 0), stream_C(0, 1), stream_C(1, 1), stream_F()]
        NS_ = len(streams)
        cur = [None] * NS_
        pend = [None] * NS_
        alive = [True] * NS_
        while any(alive):
            progressed = False
            for si in range(NS_):
                if not alive[si]:
                    continue
                if cur[si] is None:
                    if pend[si] is None:
                        try:
                            pend[si] = next(streams[si])
                        except StopIteration:
                            alive[si] = False
                            continue
                    kind, k, ready, mk = pend[si]
                    if not ready():
                        continue
                    cur[si] = (kind, k, mk())
                    pend[si] = None
                kind, k, gen = cur[si]
                try:
                    next(gen)
                    progressed = True
                except StopIteration:
                    donef.add((kind, k))
                    cur[si] = None
                    progressed = True
            assert progressed or not any(alive), "scheduler stuck"


def stage_rwkv2(P, io, G, S, src, xa):
    vec, identb = G["vec"], G["identb"]
    GN_EPS = 64e-5
    with P.phase("rwkv2"):
        wo = P.sb([64, 16, 1024], BF16)
        P.dma("gpsimd", wo[:], io["rwkv_wo"].rearrange("(h v) f -> v h f", v=64), writes=["wo"], sem="wo")
        lnw = P.sb([128, 8, 64], F32)
        lnb = P.sb([128, 8, 64], F32)
        P.dma("sync", lnw[:], io["lnw_st"], writes=["lnw"], sem="lnw")
        P.dma("sync", lnb[:], io["lnb_st"], writes=["lnb"], sem="lnb")
        big = {}
        for nm in ("yp", "sadd", "vst", "gst"):
            big[nm] = [P.sb([128, 8, 256], F32, f"l_{nm}{b}") for b in range(2)]
        for nm in ("gyb", "gsb"):
            big[nm] = [P.sb([128, 8, 512], BF16, f"l_{nm}{b}") for b in range(2)]
        gamb = [P.sb([128, 8, 4], F32) for _ in range(2)]
        bon = [P.sb([128, 8, 4], F32) for _ in range(2)]
        xt = [P.sb([128, 8, 256], F32) for _ in range(2)]
        Sb = P.sb([128, 8, 64], BF16)
        ysb2 = [P.sb([128, 8, 64], F32) for _ in range(2)]
        ysq2 = [P.sb([128, 8, 64], F32) for _ in range(2)]
        tmpS = P.sb([128, 8, 64], F32)
        yn2 = [P.sb([128, 8, 64], F32) for _ in range(2)]
        bv2 = [P.sb([128, 8, 64], F32) for _ in range(2)]
        ob2 = [P.sb([128, 8, 64], BF16) for _ in range(2)]
        st2 = [{nm: P.sb([128, 8], F32, f"g{k_}_" + nm) for nm in ("s1", "s2", "mean", "msq", "var", "lnv", "rstd")} for k_ in range(2)]
        OT = P.sb([64, 16, 256], BF16)
        py = [P.ps([128, 512], F32) for _ in range(2)]
        pS = P.ps([128, 512], F32)
        ptr = P.ps([128, 1024], F32)
        pw = [P.ps([128, 512], F32) for _ in range(2)]
        P.memset("gpsimd", Sb[:], 0.0, ["Sb"])

        def load(k):
            isctx, idx = RW_ORDER2[k]
            tg = 16 if isctx else idx
            b = k % 2
            for nm in ("yp", "sadd", "vst", "gst", "gyb", "gsb"):
                P.dma("sync", big[nm][b][:], S[nm][tg].rearrange("o p x -> p o x"), writes=[f"{nm}{b}"], sem=f"{nm}{b}")
            P.dma("sync", gamb[b][:].rearrange("p a b -> p (a b)"), S["gamb"][tg], writes=[f"gamb{b}"], sem=f"gamb{b}")
            P.dma("sync", bon[b][:].rearrange("p a b -> p (a b)"), S["bon"][tg], writes=[f"bon{b}"], sem=f"bon{b}")
            c0 = T if isctx else idx * 256
            P.dma("sync", xt[b][:], fm(src[:, c0:c0 + 256]), writes=[f"xt{b}"], sem=f"xt{b}")

        load(0)
        for k, (isctx, idx) in enumerate(RW_ORDER2):
            b = k % 2
            if k + 1 < len(RW_ORDER2):
                load(k + 1)
            c0 = T if isctx else idx * 256
            _, _, gates = mod_scalars(G, 0, 0, isctx)
            bc = lambda ap: ap.unsqueeze(2).broadcast_to([128, 8, 64])
            def chain_part(u):
                us = slice(u * 64, (u + 1) * 64)
                q_ = u % 2
                for oc in range(8):
                    P.mm(py[q_][:, oc * 64:(oc + 1) * 64], big["gyb"][b][:, oc, u * 128:(u + 1) * 128], Sb[:, oc, :], True, True, [f"gyb{b}", "Sb"], [f"py{q_}"])
                for oc in range(8):
                    P.mm(pS[:, oc * 64:(oc + 1) * 64], big["gsb"][b][:, oc, u * 128:(u + 1) * 128], Sb[:, oc, :], True, True, [f"gsb{b}", "Sb"], ["pS"])
                pS3 = pS[:].rearrange("p (o v) -> p o v", v=64)
                P.tt("vector", tmpS[:], pS3, big["sadd"][b][:, :, us], ALU.add, ["pS", f"sadd{b}"], ["tmpS"])
                P.tt("vector", Sb[:], tmpS[:], bc(gamb[b][:, :, u]), ALU.mult, ["tmpS", f"gamb{b}"], ["Sb"])

            def read_part(u):
                us = slice(u * 64, (u + 1) * 64)
                q_ = u % 2
                ysb, ysq, yn, bv, ob, st = ysb2[q_], ysq2[q_], yn2[q_], bv2[q_], ob2[q_], st2[q_]
                N = lambda nm: f"{nm}{q_}"
                py3 = py[q_][:].rearrange("p (o v) -> p o v", v=64)
                P.tt("vector", ysb[:], py3, big["yp"][b][:, :, us], ALU.add, [f"py{q_}", f"yp{b}"], [N("ysb")])
                P.tt("gpsimd", bv[:], big["vst"][b][:, :, us], bc(bon[b][:, :, u]), ALU.mult, [f"vst{b}", f"bon{b}"], [N("bv")])
                yield
                P.op("vector", lambda e: e.tensor_reduce(out=st["s1"][:], in_=ysb[:], axis=AX.X, op=ALU.add), [N("ysb")], [N("s1")])
                P.tt("gpsimd", ysq[:], ysb[:], ysb[:], ALU.mult, [N("ysb")], [N("ysq")])
                yield
                P.op("vector", lambda e: e.tensor_reduce(out=st["s2"][:], in_=ysq[:], axis=AX.X, op=ALU.add), [N("ysq")], [N("s2")])
                P.ts("vector", st["mean"][:], st["s1"][:], 1.0 / 64, None, ALU.mult, None, [N("s1")], [N("mean")])
                P.tt("vector", st["msq"][:], st["mean"][:], st["mean"][:], ALU.mult, [N("mean")], [N("msq")])
                P.stt(st["var"][:], st["s2"][:], 1.0 / 64, st["msq"][:], ALU.mult, ALU.subtract, [N("s2"), N("msq")], [N("var")])
                yield
                P.act(st["lnv"][:], st["var"][:], AF.Ln, [N("var")], [N("lnv")], bias=GN_EPS)
                P.act(st["rstd"][:], st["lnv"][:], AF.Exp, [N("lnv")], [N("rstd")], scale=-0.5)
                P.tt("gpsimd", yn[:], ysb[:], bc(st["mean"][:]), ALU.subtract, [N("ysb"), N("mean")], [N("yn")])
                yield
                P.tt("vector", yn[:], yn[:], bc(st["rstd"][:]), ALU.mult, [N("yn"), N("rstd")], [N("yn")])
                yield
                P.tt("gpsimd", yn[:], yn[:], lnw[:], ALU.mult, [N("yn"), "lnw"], [N("yn")])
                yield
                P.tt("vector", yn[:], yn[:], lnb[:], ALU.add, [N("yn"), "lnb"], [N("yn")])
                yield
                P.tt("gpsimd", yn[:], yn[:], bv[:], ALU.add, [N("yn"), N("bv")], [N("yn")])
                yield
                P.tt("vector", ob[:], yn[:], big["gst"][b][:, :, us], ALU.mult, [N("yn"), f"gst{b}"], [N("ob")])
                yield
                ptb = ptr[:].bitcast(BF16)
                for oc in range(8):
                    P.tr(ptb[0:64, oc * 128:(oc + 1) * 128], ob[:, oc, :], identb[:], [N("ob"), "identb"], ["ptr"])
                P.cp("scalar", OT[:, :, us], ptb[0:64, 0:1024].rearrange("p (h t) -> p h t", t=64), ["ptr"], ["OT"])
                yield

            def chain_all():
                for u in range(3, -1, -1):
                    chain_part(u)
                    yield

            jobs = [read_part(u) for u in range(3, -1, -1)]
            cgen = chain_all()
            next(cgen)
            active = []
            started = 0
            while jobs or active:
                while jobs and len(active) < 2:
                    if started >= 1:
                        try:
                            next(cgen)
                        except StopIteration:
                            pass
                    active.append(jobs.pop(0))
                    started += 1
                for gen in list(active):
                    try:
                        next(gen)
                    except StopIteration:
                        active.remove(gen)
            for oc in range(8):
                j = oc % 2
                for h in range(16):
                    P.mm(pw[j][:, 0:256], wo[:, h, oc * 128:(oc + 1) * 128], OT[:, h, :], h == 0, h == 15, ["wo", "OT"], [f"pw{j}"])
                P.stt(xt[b][:, oc, :], pw[j][:, 0:256], gates[oc], xt[b][:, oc, :], ALU.mult, ALU.add, [f"pw{j}", f"xt{b}", "modv"], [f"xt{b}"])
            P.dma("sync", fm(xa[:, c0:c0 + 256]), xt[b][:], reads=[f"xt{b}"], writes=[("xa", k)], sem=f"xt{b}")


def stage_qkv(P, io, G, hb, qtd, Kz, VA):
    vec, bones, perm = G["vec"], G["bones"], G["perm"]
    with P.phase("qkv"):
        wq = P.sb([128, 8, 1024], BF16)
        wkd = P.sb([128, 8, 512], BF16)
        wv = P.sb([128, 8, 256], BF16)
        P.dma("gpsimd", wq[:], fm(io["attn_wq"]), writes=["wq"], sem="wq")
        P.dma("gpsimd", wkd[:], fm(io["attn_wkd"]), writes=["wkd"], sem="wkd")
        P.dma("gpsimd", wv[:], fm(io["attn_wv"]), writes=["wv"], sem="wv")
        ht = [P.sb([128, 8, 512], BF16) for _ in range(2)]
        cs = [P.sb([128, 512], F32) for _ in range(2)]
        sn = [P.sb([128, 512], F32) for _ in range(2)]
        NB = 2
        qf = [P.sb([128, 512], F32) for _ in range(NB)]
        sqb = [P.sb([128, 512], BF16) for _ in range(NB)]
        lnv = [P.sb([128, 512], F32) for _ in range(NB)]
        rstd = [P.sb([128, 512], F32) for _ in range(NB)]
        qh = [P.sb([128, 512], F32) for _ in range(NB)]
        qhb = [P.sb([128, 512], BF16) for _ in range(NB)]
        t1 = [P.sb([128, 512], F32) for _ in range(NB)]
        t2 = [P.sb([128, 512], F32) for _ in range(NB)]
        qst = [P.sb([128, 8, 512], BF16) for _ in range(2)]
        pp = [P.ps([128, 512], F32) for _ in range(6)]
        cnt = [0, 0]

        def nxt():
            cnt[0] += 1
            return cnt[0] % 6

        P.memset("gpsimd", VA[:], 0.0, ["VA0"])
        P.memset("gpsimd", VA[:].rearrange("p k (j x) -> p k j x", x=65)[:, :, 0:5, 64:65], 1.0, ["VA0"])
        P.memset("gpsimd", Kz[0][64:128, :, :], 0.0, ["Kz0z"])
        P.memset("gpsimd", Kz[1][0:64, :, :], 0.0, ["Kz1z"])
        tiles = ALL_TILES

        def load(i):
            c0, tw, isctx = tiles[i]
            b = i % 2
            P.dma("sync", ht[b][:, :, :tw], fm(hb[:, c0:c0 + tw]), writes=[f"ht{b}"], sem=f"ht{b}")
            if not isctx:
                P.dma("sync", cs[b][:, :tw], io["cosT"][:, c0:c0 + tw], writes=[f"cs{b}"], sem=f"cs{b}")
                P.dma("sync", sn[b][:, :tw], io["sinT"][:, c0:c0 + tw], writes=[f"sn{b}"], sem=f"sn{b}")

        def normrope(wcols, nscal, dsts, b, tw, isctx, wname, dres="dstqk"):
            cnt[1] += 1
            n = cnt[1] % NB
            i = nxt()
            for c in range(8):
                P.mm(pp[i][:, :tw], wcols(c), ht[b][:, c, :tw], c == 0, c == 7, [wname, f"ht{b}"], [f"pp{i}"])
            P.cp("scalar", qf[n][:, :tw], pp[i][:, :tw], [f"pp{i}"], [f"qf{n}"])
            P.act(sqb[n][:, :tw], qf[n][:, :tw], AF.Square, [f"qf{n}"], [f"sqb{n}"])
            yield
            i = nxt()
            P.mm(pp[i][:, :tw], bones[:], sqb[n][:, :tw], True, True, ["bones", f"sqb{n}"], [f"pp{i}"])
            P.act(lnv[n][:, :tw], pp[i][:, :tw], AF.Ln, [f"pp{i}"], [f"lnv{n}"], bias=1e-6, scale=1.0 / 64)
            P.act(rstd[n][:, :tw], lnv[n][:, :tw], AF.Exp, [f"lnv{n}"], [f"rstd{n}"], scale=-0.5)
            yield
            P.stt(qh[n][:, :tw], qf[n][:, :tw], nscal, rstd[n][:, :tw], ALU.mult, ALU.mult, [f"qf{n}", f"rstd{n}", "vec"], [f"qh{n}"])
            if isctx:
                for dst, sl in dsts:
                    P.cp("gpsimd", dst, qh[n][sl, :tw], [f"qh{n}"], [dres])
                return
            P.cp("gpsimd", qhb[n][:, :tw], qh[n][:, :tw], [f"qh{n}"], [f"qhb{n}"])
            yield
            i = nxt()
            P.mm(pp[i][:, :tw], perm[:], qhb[n][:, :tw], True, True, ["perm", f"qhb{n}"], [f"pp{i}"])
            P.tt("gpsimd", t1[n][:, :tw], qh[n][:, :tw], cs[b][:, :tw], ALU.mult, [f"qh{n}", f"cs{b}"], [f"t1{n}"])
            P.tt("vector", t2[n][:, :tw], pp[i][:, :tw], sn[b][:, :tw], ALU.mult, [f"pp{i}", f"sn{b}"], [f"t2{n}"])
            yield
            for dst, sl in dsts:
                P.tt("gpsimd", dst, t1[n][sl, :tw], t2[n][sl, :tw], ALU.add, [f"t1{n}", f"t2{n}"], [dres])

        ALLP = slice(0, 128)
        load(0)
        for i, (c0, tw, isctx) in enumerate(tiles):
            b = i % 2
            if i + 1 < len(tiles):
                load(i + 1)
            jobs = []
            if not isctx:
                for oc in range(8):
                    jobs.append(normrope(lambda c, oc=oc: wq[:, c, oc * 128:(oc + 1) * 128], vec[:, 19, oc:oc + 1], [(qst[b][:, oc, :tw], ALLP)], b, tw, False, "wq",
                                         dres=(f"qst{b}", oc)))
            for g in range(4):
                jobs.append(normrope(lambda c, g=g: wkd[:, c, g * 128:(g + 1) * 128], vec[:, 20, 0:1],
                                     [(Kz[0][0:64, g, c0:c0 + tw], slice(0, 64)), (Kz[1][64:128, g, c0:c0 + tw], slice(64, 128))], b, tw, isctx, "wkd"))

            def vjob():
                for sub in range(tw // 128):
                    kt = c0 // 128 + sub
                    j = nxt()
                    for c in range(8):
                        P.mm(pp[j][:, 0:256], ht[b][:, c, sub * 128:(sub + 1) * 128], wv[:, c, :], c == 0, c == 7, ["wv", f"ht{b}"], [f"pp{j}"])
                    P.cp("scalar", VA[:, kt, 65:325].rearrange("p (g x) -> p g x", x=65)[:, :, 0:64],
                         pp[j][:, 0:256].rearrange("p (g d) -> p g d", d=64), [f"pp{j}", "VA0"], [("VA", kt)])
                    yield

            jobs.append(vjob())
            active = []
            while jobs or active:
                while jobs and len(active) < 2:
                    active.append(jobs.pop(0))
                for gen in list(active):
                    try:
                        next(gen)
                    except StopIteration:
                        active.remove(gen)
            if not isctx:
                P.dma("sync", fm(qtd[:, c0:c0 + tw]), qst[b][:, :, :tw], reads=[(f"qst{b}", oc) for oc in range(8)], writes=[("qtd", i)], sem=f"qst{b}")


def stage_attn(P, io, G, qtd, Kz, VA, xa):
    with P.phase("attn"):
        wo = P.sb([128, 8, 1024], BF16)
        P.dma("gpsimd", wo[:], fm(io["attn_wo"]), writes=["wo"], sem="wo")
        sel = P.sb([128, 2, 128], F32)
        P.dma("sync", sel[:], io["c_sel"], writes=["sel"], sem="sel")
        PT = [P.sb([128, 1024], BF16) for _ in range(3)]
        osb = [P.sb([128, 512], F32) for _ in range(2)]
        rb = [P.sb([128, 512], F32) for _ in range(2)]
        xt = P.sb([128, 8, 512], F32)
        QB = [P.sb([128, 8, 512], BF16) for _ in range(2)]
        psS = [P.ps([128, 1024], F32) for _ in range(2)]
        psO = [P.ps([128, 512], F32) for _ in range(2)]
        psB = P.ps([128, 512], F32)
        pX = [P.ps([128, 512], F32) for _ in range(1)]
        _, _, gates = mod_scalars(G, 1, 0, False)
        for k in range(2):
            P.memset("gpsimd", osb[k][:], 0.0, [f"osb{k}"])
        def loadq(qb):
            P.dma("sync", QB[qb % 2][:], fm(qtd[:, qb * 512:(qb + 1) * 512]), writes=[("QT", h, qb) for h in range(16)], sem=f"QB{qb % 2}")

        loadq(0)
        for qb in range(8):
            qsl = slice(qb * 512, (qb + 1) * 512)
            QT = QB[qb % 2]
            if qb + 1 < 8:
                loadq(qb + 1)
            P.dma("sync", xt[:], fm(xa[:, qsl]), writes=["xt"], sem="xt")
            steps = [(h, kp) for h in range(16) for kp in range(17)]

            def S(i):
                h, kp = steps[i]
                g, oc, h2 = h // 4, h // 2, h % 2
                for e_ in range(2):
                    kt = 2 * kp + e_
                    P.mm(psS[i % 2][:, e_ * 512:(e_ + 1) * 512], Kz[h2][:, g, kt * 128:(kt + 1) * 128], QT[:, oc, :], True, True,
                         ["Kz", ("QT", h, qb)], [f"psS{i % 2}"])

            def epi_a(h):
                o = h % 2
                P.cp("vector", osb[o][:], psO[o][:], [f"psO{o}"], [f"osb{o}"])

            def epi_b(h):
                oc, h2, o = h // 2, h % 2, h % 2
                hs = slice(h2 * 64, h2 * 64 + 64)
                P.mm(psB[:, :], sel[:, h2, :], osb[o][:], True, True, ["sel", f"osb{o}"], ["psB"])
                P.act(rb[o][hs, :], psB[hs, :], AF.Ln, ["psB"], [f"rb{o}"])
                P.act(rb[o][hs, :], rb[o][hs, :], AF.Exp, [f"rb{o}"], [f"rb{o}"], scale=-1.0)
                P.tt("gpsimd", QT[hs, oc, :], osb[o][hs, :], rb[o][hs, :], ALU.mult, [f"osb{o}", f"rb{o}"], [("QT", h, qb)])

            S(0)
            pend = {}
            for i, (h, kp) in enumerate(steps):
                g, h2, o = h // 4, h % 2, h % 2
                if i + 1 < len(steps):
                    S(i + 1)
                p_ = i % 3
                P.act(PT[p_][:], psS[i % 2][:, :], AF.Exp, [f"psS{i % 2}"], [f"PT{p_}"], scale=0.125)
                v0 = 65 + 65 * g if h2 == 0 else 1 + 65 * g
                for e_ in range(2):
                    kt = 2 * kp + e_
                    P.mm(psO[o][:, :], VA[:, kt, v0:v0 + 128], PT[p_][:, e_ * 512:(e_ + 1) * 512], kt == 0, kt == 33, [f"PT{p_}", "VA"], [f"psO{o}"])
                if kp == 16:
                    epi_a(h)
                    pend[i + 3] = h
                if i in pend:
                    epi_b(pend.pop(i))
            for k in sorted(pend):
                epi_b(pend[k])
            for oc in range(8):
                j = 0
                for c in range(8):
                    P.mm(pX[j][:, :], wo[:, c, oc * 128:(oc + 1) * 128], QT[:, c, :], c == 0, c == 7,
                         ["wo", ("QT", 2 * c, qb), ("QT", 2 * c + 1, qb)], [f"pX{j}"])
                P.stt(xt[:, oc, :], pX[j][:, :], gates[oc], xt[:, oc, :], ALU.mult, ALU.add, [f"pX{j}", "xt", "modv"], ["xt"])
            P.dma("sync", fm(xa[:, qsl]), xt[:], reads=["xt"], writes=[("xa", qb)], sem="xt")


IN_SHAPES = {
    "xin": [D, TT], "cvec": [128, 8, 2], "w_mod": [2, D, 6 * D], "b_mod": [2, 6 * D], "vecs": [128, NV, 8],
    "mlp_w1": [2, D, 4 * D], "mlp_w2": [2, 4 * D, D],
    "rwkv_wr": [D, D], "rwkv_wk": [D, D], "rwkv_wv": [D, D], "rwkv_wo": [D, D],
    "rwkv_w1": [2, D, 64], "rwkv_w2": [2, 64, D], "rwkv_a1": [2, D, 64], "rwkv_a2": [2, 64, D],
    "rwkv_g1": [D, 128], "rwkv_g2": [128, D], "lnw_st": [128, 8, 64], "lnb_st": [128, 8, 64],
    "attn_wq": [D, D], "attn_wkd": [D, 512], "attn_wv": [D, 256], "attn_wo": [D, D],
    "cosT": [128, T], "sinT": [128, T],
    "c_ident": [128, 128], "c_ones": [128, 128], "c_bones": [128, 128], "c_masks": [128, 4, 128],
    "c_perm": [128, 128], "c_rmask": [128, 256], "c_sel": [128, 2, 128],
}


class IO(dict):
    def __init__(self, nc):
        super().__init__()
        self.nc = nc
        self.used = []

    def __missing__(self, k):
        ap = self.nc.dram_tensor(k, IN_SHAPES[k], F32, kind="ExternalInput").ap()
        self[k] = ap
        self.used.append(k)
        return ap

    def scratch(self, name, shape, dtype):
        return self.nc.dram_tensor(name, list(shape), dtype, kind="Internal").ap()

    def output(self, name, shape, dtype=F32):
        return self.nc.dram_tensor(name, list(shape), dtype, kind="ExternalOutput").ap()


def build(stages="all", dbg=None):
    nc = bass.Bass("TRN2", target_bir_lowering=False)
    io = IO(nc)
    P = Prog(nc)
    G = {}
    outs = {}
    stage_init(P, io, G)
    xa = io.scratch("xa", [D, TT], F32)
    hb = io.scratch("hb", [D, TT], BF16)
    if stages == "t_mlp":
        outs["dbg_h"] = io.output("dbg_h", [D, TT], BF16)
        stage_norm(P, io, G, "n_t", io["xin"], ALL_TILES,
                   lambda ic: mod_scalars(G, 0, 1, ic)[0], lambda ic: mod_scalars(G, 0, 1, ic)[1],
                   lambda c0, tw, ic: fm(hb[:, c0:c0 + tw]), BF16)
        with P.phase("copy"):
            P.dma("sync", xa, io["xin"], writes=["xa"], sem="cpa")
            P.dma("sync", outs["dbg_h"], hb, writes=["o"], sem="cpb")
        stage_mlp(P, io, G, 0, ALL_TILES, xa, hb)
        outs["y"] = io.output("y", [D, TT])
        fin = [G["vec"][:, 4, c:c + 1] for c in range(8)]
        stage_norm(P, io, G, "final", xa, ALL_TILES, lambda ic: fin, lambda ic: None,
                   lambda c0, tw, ic: fm(outs["y"][:, c0:c0 + tw]), F32)
    if stages in ("all", "l0", "l1pre"):
        hp = io.scratch("hp", [D, 4608], F32)
        S = rw_scratch(io)
        with P.phase("zpad"):
            z = P.sb([128, 8, 64], F32)
            P.memset("vector", z[:], 0.0, ["z"])
            for k, o in enumerate((0, 64 + T, 4224, 4288 + C)):
                P.dma("sync", fm(hp[:, o:o + 64]), z[:], reads=["z"], writes=[("hpz", k)], sem=f"z{k}")

        def hdst(c0, tw, ic):
            o = 4288 if ic else 64 + c0
            return fm(hp[:, o:o + tw])

        def hbdst(c0, tw, ic):
            return fm(hb[:, c0:c0 + tw])

        def ms(l, kind, which):
            return lambda ic: mod_scalars(G, l, kind, ic)[which]

        stage_norm(P, io, G, "n_mix0", io["xin"], ALL_TILES, ms(0, 0, 0), ms(0, 0, 1), hdst, F32)
        stage_rwkv1a(P, io, G, hp, S)
        stage_rwkv1b(P, io, G, S)
        stage_rwkv2(P, io, G, S, io["xin"], xa)
        stage_norm(P, io, G, "n_mlp0", xa, ALL_TILES, ms(0, 1, 0), ms(0, 1, 1), hbdst, BF16)
        stage_mlp(P, io, G, 0, ALL_TILES, xa, hb)
        if stages == "l0":
            outs["y"] = io.output("y", [D, TT])
            with P.phase("copyout"):
                P.dma("sync", outs["y"], xa, writes=["o"], sem="cpa")
        else:
            stage_norm(P, io, G, "n_mix1", xa, ALL_TILES, ms(1, 0, 0), ms(1, 0, 1), hbdst, BF16)
            with P.scope():
                QT = io.scratch("qtd", [D, T], BF16)
                Kz = [P.ssb([128, 4, TT], BF16, f"Kz{k}") for k in range(2)]
                VA = P.ssb([128, 34, 390], BF16, "VA")
                stage_qkv(P, io, G, hb, QT, Kz, VA)
                stage_attn(P, io, G, QT, Kz, VA, xa)
            if stages == "l1pre":
                outs["y"] = io.output("y", [D, TT])
                with P.phase("copyout"):
                    P.dma("sync", outs["y"], xa, writes=["o"], sem="cpa")
            else:
                stage_norm(P, io, G, "n_mlp1", xa, LAT_TILES, ms(1, 1, 0), ms(1, 1, 1), hbdst, BF16)
                stage_mlp(P, io, G, 1, LAT_TILES, xa, hb)
                outs["y"] = io.output("y", [D, T])
                fin = [G["vec"][:, 4, c:c + 1] for c in range(8)]
                stage_norm(P, io, G, "final", xa, LAT_TILES, lambda ic: fin, lambda ic: None,
                           lambda c0, tw, ic: fm(outs["y"][:, c0:c0 + tw]), F32)
    if stages == "t_rwkv":
        hp = io.scratch("hp", [D, 4608], F32)
        S = rw_scratch(io)
        with P.phase("zpad"):
            z = P.sb([128, 8, 64], F32)
            P.memset("vector", z[:], 0.0, ["z"])
            for k, o in enumerate((0, 64 + T, 4224, 4288 + C)):
                P.dma("sync", fm(hp[:, o:o + 64]), z[:], reads=["z"], writes=[("hpz", k)], sem=f"z{k}")
        def hdst(c0, tw, ic):
            o = 4288 if ic else 64 + c0
            return fm(hp[:, o:o + tw])
        stage_norm(P, io, G, "n_mix0", io["xin"], ALL_TILES,
                   lambda ic: mod_scalars(G, 0, 0, ic)[0], lambda ic: mod_scalars(G, 0, 0, ic)[1], hdst, F32)
        stage_rwkv1(P, io, G, hp, S)
        stage_rwkv2(P, io, G, S, io["xin"], xa)
        outs["y"] = io.output("y", [D, TT])
        with P.phase("copyout"):
            P.dma("sync", outs["y"], xa, writes=["o"], sem="cpa")
    P.close()
    return nc, io.used, list(outs.keys()), P


def fmv(v):
    return np.ascontiguousarray(np.asarray(v, np.float32).reshape(8, 128).T)


def host_consts():
    c = {}
    c["c_ident"] = np.eye(128, dtype=np.float32)
    c["c_ones"] = np.ones((128, 128), np.float32)
    blk = np.zeros((128, 128), np.float32)
    blk[:64, :64] = 1
    blk[64:, 64:] = 1
    c["c_bones"] = blk
    i = np.arange(64)
    us = (i[:, None] < i[None, :]).astype(np.float32)
    ui = (i[:, None] <= i[None, :]).astype(np.float32)
    m = np.zeros((128, 4, 128), np.float32)
    for k, mk in enumerate([us, ui, us.T, ui.T]):
        m[:64, k, :64] = mk
        m[64:, k, 64:] = mk
    c["c_masks"] = m
    Pm = np.zeros((128, 128), np.float32)
    for d in range(128):
        if d % 32 < 16:
            Pm[d, d + 16] = -1.0
        else:
            Pm[d, d - 16] = 1.0
    c["c_perm"] = np.ascontiguousarray(Pm.T)
    sel = np.zeros((128, 2, 128), np.float32)
    sel[64, 0, :] = 1.0
    sel[63, 1, :] = 1.0
    c["c_sel"] = sel
    rm = np.ones((128, 256), np.float32)
    rm[:, ::64] = 0
    c["c_rmask"] = rm
    t = np.arange(T)
    row = (t // 64).astype(np.float32)
    col = (t % 64).astype(np.float32)
    freqs = (np.float32(10000.0) ** (-np.arange(0, 32, 2, dtype=np.float32) / np.float32(32))).astype(np.float32)
    ang = np.zeros((64, T), np.float32)
    for d in range(64):
        pos = row if d < 32 else col
        ang[d] = pos * freqs[d % 16]
    c["cosT"] = np.ascontiguousarray(np.concatenate([np.cos(ang), np.cos(ang)], 0).astype(np.float32))
    c["sinT"] = np.ascontiguousarray(np.concatenate([np.sin(ang), np.sin(ang)], 0).astype(np.float32))
    return c


def host_inputs(inp, b):
    f = lambda k: np.asarray(inp[k], np.float32)
    d = {}
    d["xin"] = np.ascontiguousarray(np.concatenate([f("x")[b].T, f("ctx")[b].T], axis=1))
    d["cvec"] = np.ascontiguousarray(np.stack([fmv(f("c")[b]), fmv(f("c_ctx"))], axis=-1))
    return d


def host_shared(inp):
    f = lambda k: np.asarray(inp[k], np.float32)
    s = dict(host_consts())
    s["w_mod"] = f("w_mod")
    s["b_mod"] = f("b_mod")
    vl = [f("norm_mix")[0], f("norm_mix")[1], f("norm_mlp")[0], f("norm_mlp")[1], f("final_norm")]
    vl += [f("rwkv_mu")[0, j] for j in range(6)]
    vl += [f("rwkv_w0")[0, 0], f("rwkv_w0")[0, 1], f("rwkv_a0")[0, 0], f("rwkv_a0")[0, 1]]
    vl += [f("rwkv_k_k")[0], f("rwkv_k_a")[0], np.zeros(D, np.float32), f("rwkv_r_k")[0].reshape(-1)]
    vl += [np.tile(f("attn_q_norm")[0], 16), np.tile(f("attn_k_norm")[0], 16)]
    assert len(vl) == NV
    s["vecs"] = np.ascontiguousarray(np.stack([fmv(v) for v in vl], axis=1))
    s["mlp_w1"] = f("mlp_w1")
    s["mlp_w2"] = f("mlp_w2")
    for k in ("wr", "wk", "wv", "wo", "w1", "w2", "a1", "a2", "g1", "g2"):
        s["rwkv_" + k] = f("rwkv_" + k)[0]
    lw = f("rwkv_ln_w")[0].reshape(8, 2, 64)
    lb = f("rwkv_ln_b")[0].reshape(8, 2, 64)
    s["lnw_st"] = np.ascontiguousarray(np.repeat(lw.transpose(1, 0, 2), 64, axis=0))
    s["lnb_st"] = np.ascontiguousarray(np.repeat(lb.transpose(1, 0, 2), 64, axis=0))
    wqkv = f("attn_wqkv")[0]
    s["attn_wq"] = np.ascontiguousarray(wqkv[:, :1024])
    wk = wqkv[:, 1024:1280].reshape(D, 4, 64)
    s["attn_wkd"] = np.ascontiguousarray(np.concatenate([wk, wk], axis=2).reshape(D, 512))
    s["attn_wv"] = np.ascontiguousarray(wqkv[:, 1280:1536])
    s["attn_wo"] = f("attn_wo")[0]
    return s


_CACHE = {}


def kernel(**inputs):
    if "prog" not in _CACHE:
        _CACHE["prog"] = build("all")
    nc, used, outnames, _ = _CACHE["prog"]
    shared = host_shared(inputs)
    in_maps = []
    for b in range(NCORES):
        hi = host_inputs(inputs, b)
        hi.update(shared)
        in_maps.append({k: hi[k] for k in used})
    res = run_bass_kernel_spmd(nc, in_maps, core_ids=list(range(NCORES)))
    out = np.stack([np.ascontiguousarray(res.results[b]["y"].T) for b in range(NCORES)], axis=0)
    return out.astype(np.float32)
```

```python
from contextlib import ExitStack, contextmanager
import re as re_mod
import numpy as np
import concourse.bass as bass
import concourse.mybir as mybir
from concourse.bass_utils import run_bass_kernel_spmd

F32 = mybir.dt.float32
BF16 = mybir.dt.bfloat16
AF = mybir.ActivationFunctionType
ALU = mybir.AluOpType
AX = mybir.AxisListType

D = 1024
T = 4096
C = 256
TT = T + C
NCORES = 8
C0 = float(np.exp(-0.5))
NV = 21
ENGS = ("tensor", "vector", "scalar", "gpsimd", "sync")


class Prog:
    def __init__(self, nc):
        self.nc = nc
        self.ges = ExitStack()
        self.sems = {}
        self.cnt = {}
        self.dpool = {False: [], True: []}
        self.seen = {e: {} for e in ENGS}
        self.n = 0
        self.pes = None
        self.total_ops = 0

    def _alloc(self, es, fn, shape, dtype, name):
        self.n += 1
        return es.enter_context(fn(name or f"t{self.n}", list(shape), dtype))

    def gsb(self, shape, dtype, name=None):
        return self._alloc(self.ges, self.nc.sbuf_tensor, shape, dtype, name)

    def sb(self, shape, dtype, name=None):
        return self._alloc(self.pes, self.nc.sbuf_tensor, shape, dtype, name)

    @contextmanager
    def scope(self):
        self.ses = ExitStack()
        yield self
        self.ses.close()
        self.ses = None

    def ssb(self, shape, dtype, name=None):
        return self._alloc(self.ses, self.nc.sbuf_tensor, shape, dtype, name)

    def ps(self, shape, dtype, name=None):
        return self._alloc(self.pes, self.nc.psum_tensor, shape, dtype, name)

    @contextmanager
    def phase(self, name):
        self.ops = []
        self.last_w = {}
        self.readers = {}
        self.last_dma = {}
        self.pes = ExitStack()
        self.pname = name
        yield self
        self._emit()
        self.pes.close()
        self.pes = None

    _PSUM_RE = re_mod.compile(r"^(pp|pa|pb|pq|pf|ps\w*|pX|py|pS|ptr|pw)\d*$")

    ns = None
    ns_set = frozenset()

    def _deps(self, reads, writes):
        if self.ns is not None:
            reads = tuple((r, self.ns) if r in self.ns_set else r for r in reads)
            writes = tuple((w, self.ns) if w in self.ns_set else w for w in writes)
        extra = tuple(r for r in reads if isinstance(r, str) and self._PSUM_RE.match(r) and r not in writes)
        if extra:
            writes = tuple(writes) + extra
        deps = {}
        for r in reads:
            if r in self.last_w:
                deps.setdefault(self.last_w[r], set()).add("RAW")
        for w in writes:
            if w in self.last_w:
                deps.setdefault(self.last_w[w], set()).add("WAW")
            for rd in self.readers.get(w, ()):
                deps.setdefault(rd, set()).add("WAR")
        idx = len(self.ops)
        for r in reads:
            self.readers.setdefault(r, []).append(idx)
        for w in writes:
            self.last_w[w] = idx
            self.readers[w] = []
        return deps

    def op(self, eng, fn, reads=(), writes=()):
        deps = self._deps(tuple(reads), tuple(writes))
        self.ops.append(dict(eng=eng, fn=fn, deps=deps, dma=None))
        return len(self.ops) - 1

    def dma(self, queue, out, in_, reads=(), writes=(), sem=None):
        deps = self._deps(tuple(reads), tuple(writes))
        prev = self.last_dma.get(sem)
        if prev is not None:
            deps.setdefault(prev, set()).add("SER")
        idx = len(self.ops)
        self.last_dma[sem] = idx
        self.ops.append(dict(eng=queue, fn=lambda e: e.dma_start(out=out, in_=in_), deps=deps, dma=sem))
        return idx

    def _emit(self):
        nc = self.nc
        ops = self.ops
        if self.last_dma:
            ops.append(dict(eng="sync", fn=None, deps={i: {"FIN"} for i in self.last_dma.values()}, dma=None))
        self.total_ops += len(ops)

        def needs_wait(x, d, kinds):
            if d["dma"] is not None or x["dma"] is not None:
                return True
            if d["eng"] != x["eng"]:
                return True
            if x["eng"] == "tensor":
                return False
            return bool(kinds & {"RAW", "FIN"})

        signal = [False] * len(ops)
        for x in ops:
            for di, kinds in x["deps"].items():
                d = ops[di]
                if d["dma"] is None and needs_wait(x, d, kinds):
                    signal[di] = True
        dkeys = {}
        nk = {False: 0, True: 0}
        for o in ops:
            if o["dma"] is not None and o["dma"] not in dkeys:
                sw = o["eng"] == "gpsimd"
                dkeys[o["dma"]] = (sw, nk[sw])
                nk[sw] += 1
        for sw in (False, True):
            while len(self.dpool[sw]) < nk[sw]:
                h = self.ges.enter_context(nc.semaphore(f"dq{int(sw)}_{len(self.dpool[sw])}"))
                self.dpool[sw].append([h, 0])
        for e in ENGS:
            if e not in self.sems:
                self.sems[e] = self.ges.enter_context(nc.semaphore(f"e_{e}"))
        token = [None] * len(ops)
        for i, o in enumerate(ops):
            if o["dma"] is not None:
                dk = dkeys[o["dma"]]
                slot = self.dpool[dk[0]][dk[1]]
                slot[1] += 16
                token[i] = (("d", dk), slot[1])
            elif signal[i]:
                self.cnt[o["eng"]] = self.cnt.get(o["eng"], 0) + 1
                token[i] = (("e", o["eng"]), self.cnt[o["eng"]])
        per_eng = {e: [] for e in ENGS}
        for i, o in enumerate(ops):
            per_eng[o["eng"]].append(i)

        def semh(key):
            return self.dpool[key[1][0]][key[1][1]][0] if key[0] == "d" else self.sems[key[1]]

        def run(engname, eng):
            seen = self.seen[engname]
            for i in per_eng[engname]:
                o = ops[i]
                waits = {}
                for di, kinds in o["deps"].items():
                    d = ops[di]
                    if not needs_wait(o, d, kinds):
                        continue
                    key, val = token[di]
                    if waits.get(key, 0) < val:
                        waits[key] = val
                for key, val in waits.items():
                    if seen.get(key, 0) >= val:
                        continue
                    seen[key] = val
                    eng.wait_ge(semh(key), val)
                if o["fn"] is None:
                    continue
                ins = o["fn"](eng)
                if o["dma"] is not None:
                    ins.then_inc(semh(token[i][0]), 16)
                elif signal[i]:
                    ins.then_inc(self.sems[engname], 1)

        with nc.Block() as block:
            @block.sync
            def _(e):
                run("sync", e)

            @block.tensor
            def _(e):
                run("tensor", e)

            @block.vector
            def _(e):
                run("vector", e)

            @block.scalar
            def _(e):
                run("scalar", e)

            @block.gpsimd
            def _(e):
                run("gpsimd", e)

    def close(self):
        self.ges.close()

    def mm(self, out, lhsT, rhs, start, stop, r, w):
        self.op("tensor", lambda e: e.matmul(out, lhsT=lhsT, rhs=rhs, start=start, stop=stop), r, w)

    def tr(self, out, in_, ident, r, w):
        self.op("tensor", lambda e: e.transpose(out, in_, ident), r, w)

    def tt(self, eng, out, in0, in1, op, r, w):
        self.op(eng, lambda e: e.tensor_tensor(out=out, in0=in0, in1=in1, op=op), r, w)

    def ts(self, eng, out, in0, s1, s2, op0, op1, r, w):
        if op1 is None:
            self.op(eng, lambda e: e.tensor_scalar(out=out, in0=in0, scalar1=s1, scalar2=None, op0=op0), r, w)
        else:
            self.op(eng, lambda e: e.tensor_scalar(out=out, in0=in0, scalar1=s1, scalar2=s2, op0=op0, op1=op1), r, w)

    def stt(self, out, in0, scalar, in1, op0, op1, r, w):
        self.op("vector", lambda e: e.scalar_tensor_tensor(out=out, in0=in0, scalar=scalar, in1=in1, op0=op0, op1=op1), r, w)

    def act(self, out, in_, func, r, w, bias=None, scale=None):
        kw = {}
        if bias is not None:
            kw["bias"] = bias
        if scale is not None:
            kw["scale"] = scale
        self.op("scalar", lambda e: e.activation(out=out, in_=in_, func=func, **kw), r, w)

    def cp(self, eng, out, in_, r, w):
        if eng == "scalar":
            self.op(eng, lambda e: e.activation(out=out, in_=in_, func=AF.Copy), r, w)
        else:
            self.op(eng, lambda e: e.tensor_copy(out=out, in_=in_), r, w)

    def memset(self, eng, ap, val, w):
        self.op(eng, lambda e: e.memset(ap, val), (), w)


def fm(ap2d):
    return ap2d.rearrange("(c p) n -> p c n", p=128)


LAT_TILES = [(i * 512, 512, False) for i in range(8)]
ALL_TILES = LAT_TILES + [(T, 256, True)]


def stage_init(P, io, G):
    nc = P.nc
    G["identf"] = P.gsb([128, 128], F32, "identf")
    G["identb"] = P.gsb([128, 128], BF16, "identb")
    G["onesb"] = P.gsb([128, 128], BF16, "onesb")
    G["bones"] = P.gsb([128, 128], BF16, "bones")
    G["masks"] = P.gsb([128, 4, 128], BF16, "masks")
    G["perm"] = P.gsb([128, 128], BF16, "perm")
    G["rmask"] = P.gsb([128, 256], F32, "rmask")
    G["vec"] = P.gsb([128, NV, 8], F32, "vec")
    G["modv"] = P.gsb([128, 2, 6, 8, 2], F32, "modv")
    G["gg"] = P.gsb([128, 2, 2, 8, 2], F32, "gg")
    with P.phase("init"):
        P.dma("sync", G["identf"][:], io["c_ident"], writes=["identf"], sem="identf")
        P.dma("sync", G["rmask"][:], io["c_rmask"], writes=["rmask"], sem="rmask")
        P.dma("sync", G["vec"][:], io["vecs"], writes=["vec"], sem="vec")
        P.dma("gpsimd", G["identb"][:], io["c_ident"], writes=["identb"], sem="identb")
        P.dma("gpsimd", G["onesb"][:], io["c_ones"], writes=["onesb"], sem="onesb")
        P.dma("gpsimd", G["bones"][:], io["c_bones"], writes=["bones"], sem="bones")
        P.dma("gpsimd", G["masks"][:], io["c_masks"], writes=["masks"], sem="masks")
        P.dma("gpsimd", G["perm"][:], io["c_perm"], writes=["perm"], sem="perm")
        vec = G["vec"]
        P.ts("vector", vec[:, 17, :], vec[:, 16, :], -1.0, 1.0, ALU.mult, ALU.add, ["vec"], ["vec"])
        sv = P.sb([128, 8, 2], F32)
        svs = P.sb([128, 8, 2], F32)
        P.dma("sync", sv[:], io["cvec"], writes=["sv"], sem="sv")
        P.act(svs[:], sv[:], AF.Silu, ["sv"], ["svs"])
        brow = P.sb([2, 2 * 6144], F32)
        row = P.sb([2, 2 * 6144], F32)
        P.dma("sync", brow[:], io["b_mod"].rearrange("l n -> (l n)").partition_broadcast(2), writes=["brow"], sem="brow")
        wt = [P.sb([128, 8, 512], F32) for _ in range(2)]
        psr = [P.ps([128, 512], F32) for _ in range(2)]
        pst = P.ps([128, 512], F32)
        k = 0
        for l in range(2):
            for nb in range(12):
                b = k % 2
                k += 1
                P.dma("sync", wt[b][:], fm(io["w_mod"][l, :, nb * 512:(nb + 1) * 512]), writes=[f"wt{b}"], sem=f"wt{b}")
                for c in range(8):
                    P.mm(psr[b][0:2, :], svs[:, c, :], wt[b][:, c, :], c == 0, c == 7, ["svs", f"wt{b}"], [f"psr{b}"])
                o = l * 6144 + nb * 512
                P.tt("vector", row[:, o:o + 512], psr[b][0:2, :], brow[:, o:o + 512], ALU.add, [f"psr{b}", "brow"], ["row"])
        for l in range(2):
            for blk in range(48):
                o = l * 6144 + blk * 128
                P.tr(pst[:, l * 96 + blk * 2:l * 96 + blk * 2 + 2], row[0:2, o:o + 128], G["identf"][0:2, 0:2], ["row", "identf"], ["pst"])
        P.cp("vector", G["modv"][:].rearrange("p l m c j -> p (l m c j)"), pst[:, 0:192], ["pst"], ["modv"])
        modv, gg = G["modv"], G["gg"]
        for l in range(2):
            for kind in range(2):
                sc = modv[:, l, 1 + 3 * kind, :, :]
                nv = vec[:, (0 if kind == 0 else 2) + l, :].unsqueeze(2).broadcast_to([128, 8, 2])
                P.ts("vector", gg[:, l, kind, :, :], sc, 1.0, None, ALU.add, None, ["modv"], ["gg"])
                P.tt("vector", gg[:, l, kind, :, :], gg[:, l, kind, :, :], nv, ALU.mult, ["gg", "vec"], ["gg"])


def mod_scalars(G, l, kind, isctx):
    j = 1 if isctx else 0
    gains = [G["gg"][:, l, kind, c, j:j + 1] for c in range(8)]
    shifts = [G["modv"][:, l, 3 * kind, c, j:j + 1] for c in range(8)]
    gates = [G["modv"][:, l, 3 * kind + 2, c, j:j + 1] for c in range(8)]
    return gains, shifts, gates


def stage_norm(P, io, G, name, src, tiles, gains_fn, shifts_fn, dst_fn, out_dtype):
    with P.phase(name):
        xt = [P.sb([128, 8, 512], F32) for _ in range(2)]
        sq = P.sb([128, 8, 512], BF16)
        lnv = P.sb([128, 512], F32)
        rstd = P.sb([128, 512], F32)
        tmp = [P.sb([128, 512], F32) for _ in range(2)]
        ho = [P.sb([128, 8, 512], out_dtype) for _ in range(2)]
        ps = [P.ps([128, 512], F32) for _ in range(2)]

        def load(i):
            c0, tw, _ = tiles[i]
            b = i % 2
            P.dma("sync", xt[b][:, :, :tw], fm(src[:, c0:c0 + tw]), writes=[f"xt{b}"], sem=f"xt{b}")

        load(0)
        for i, (c0, tw, isctx) in enumerate(tiles):
            b = i % 2
            if i + 1 < len(tiles):
                load(i + 1)
            gains = gains_fn(isctx)
            shifts = shifts_fn(isctx)
            P.act(sq[:, :, :tw], xt[b][:, :, :tw], AF.Square, [f"xt{b}"], ["sq"])
            for c in range(8):
                P.mm(ps[b][:, :tw], G["onesb"][:], sq[:, c, :tw], c == 0, c == 7, ["sq", "onesb"], [f"ps{b}"])
            P.act(lnv[:, :tw], ps[b][:, :tw], AF.Ln, [f"ps{b}"], ["lnv"], bias=1e-6, scale=1.0 / D)
            P.act(rstd[:, :tw], lnv[:, :tw], AF.Exp, ["lnv"], ["rstd"], scale=-0.5)
            for c in range(8):
                if shifts is None:
                    P.stt(ho[b][:, c, :tw], xt[b][:, c, :tw], gains[c], rstd[:, :tw], ALU.mult, ALU.mult,
                          [f"xt{b}", "rstd", "vec", "gg"], [f"ho{b}"])
                else:
                    t = tmp[c % 2]
                    P.stt(t[:, :tw], xt[b][:, c, :tw], gains[c], rstd[:, :tw], ALU.mult, ALU.mult,
                          [f"xt{b}", "rstd", "vec", "gg"], [f"tmp{c % 2}"])
                    P.act(ho[b][:, c, :tw], t[:, :tw], AF.Identity, [f"tmp{c % 2}", "modv"], [f"ho{b}"], bias=shifts[c])
            P.dma("sync", dst_fn(c0, tw, isctx), ho[b][:, :, :tw], reads=[f"ho{b}"], writes=[("dst", i)], sem=f"ho{b}")


def stage_mlp(P, io, G, l, tiles, xa, hb):
    for half in range(2):
        with P.phase(f"mlp{l}{half}"):
            w1 = P.sb([128, 8, 2048], BF16)
            w2 = P.sb([128, 16, 1024], BF16)
            for q in range(2):
                P.dma("gpsimd", w1[:, :, q * 1024:(q + 1) * 1024],
                      fm(io["mlp_w1"][l, :, half * 2048 + q * 1024: half * 2048 + (q + 1) * 1024]), writes=["w1"], sem=f"w1{q}")
                P.dma("gpsimd", w2[:, q * 8:(q + 1) * 8, :],
                      io["mlp_w2"][l, half * 2048 + q * 1024: half * 2048 + (q + 1) * 1024, :].rearrange("(f p) n -> p f n", p=128),
                      writes=["w2"], sem=f"w2{q}")
            xt = [P.sb([128, 8, 512], F32) for _ in range(2)]
            ht = [P.sb([128, 8, 512], BF16) for _ in range(2)]
            h1 = P.sb([128, 16, 512], BF16)
            r1 = [P.sb([128, 512], F32) for _ in range(2)]
            ps = [P.ps([128, 512], F32) for _ in range(4)]

            def load(i):
                c0, tw, _ = tiles[i]
                b = i % 2
                P.dma("sync", ht[b][:, :, :tw], fm(hb[:, c0:c0 + tw]), writes=[f"ht{b}"], sem=f"ht{b}")
                P.dma("sync", xt[b][:, :, :tw], fm(xa[:, c0:c0 + tw]), reads=[("xa", i)], writes=[f"xt{b}"], sem=f"xt{b}")

            load(0)
            for i, (c0, tw, isctx) in enumerate(tiles):
                b = i % 2
                if i + 1 < len(tiles):
                    load(i + 1)
                _, _, gates = mod_scalars(G, l, 1, isctx)
                for fc in range(16):
                    pb = fc % 2
                    for c in range(8):
                        P.mm(ps[pb][:, :tw], w1[:, c, fc * 128:(fc + 1) * 128], ht[b][:, c, :tw], c == 0, c == 7,
                             ["w1", f"ht{b}"], [f"ps{pb}"])
                    P.act(r1[pb][:, :tw], ps[pb][:, :tw], AF.Relu, [f"ps{pb}"], [f"r1{pb}"])
                    P.tt("gpsimd", h1[:, fc, :tw], r1[pb][:, :tw], r1[pb][:, :tw], ALU.mult, [f"r1{pb}"], [("h1", fc)])
                for oc in range(8):
                    pb = 2 + oc % 2
                    for fc in range(16):
                        P.mm(ps[pb][:, :tw], w2[:, fc, oc * 128:(oc + 1) * 128], h1[:, fc, :tw], fc == 0, fc == 15,
                             ["w2", ("h1", fc)], [f"ps{pb}"])
                    P.stt(xt[b][:, oc, :tw], ps[pb][:, :tw], gates[oc], xt[b][:, oc, :tw], ALU.mult, ALU.add,
                          [f"ps{pb}", f"xt{b}", "modv"], [f"xt{b}"])
                P.dma("sync", fm(xa[:, c0:c0 + tw]), xt[b][:, :, :tw], reads=[f"xt{b}"], writes=[("xa", i)], sem=f"xt{b}")


RW_ORDER1 = [(True, 0)] + [(False, i) for i in range(16)]
RW_ORDER2 = [(True, 0)] + [(False, i) for i in range(15, -1, -1)]


def rw_scratch(io):
    S = {}
    S["yp"] = io.scratch("rw_yp", [17, 8, 128, 256], F32)
    S["sadd"] = io.scratch("rw_sadd", [17, 8, 128, 256], F32)
    S["vst"] = io.scratch("rw_vst", [17, 8, 128, 256], F32)
    S["gst"] = io.scratch("rw_gst", [17, 8, 128, 256], F32)
    S["gyb"] = io.scratch("rw_gyb", [17, 8, 128, 512], BF16)
    S["gsb"] = io.scratch("rw_gsb", [17, 8, 128, 512], BF16)
    S["gamb"] = io.scratch("rw_gamb", [17, 128, 32], F32)
    S["bon"] = io.scratch("rw_bon", [17, 128, 32], F32)
    S["ops"] = io.scratch("rw_ops", [17, 8, 128, 2048], BF16)
    S["vb"] = io.scratch("rw_vb", [17, 8, 128, 256], BF16)
    S["gam"] = io.scratch("rw_gam", [17, 8, 128, 8], F32)
    return S


def stage_rwkv1(P, io, G, hp, S, dbg=None):
    vec, masks, identb, identf, bones, onesb, rmask = (G[k] for k in ("vec", "masks", "identb", "identf", "bones", "onesb", "rmask"))
    with P.phase("rwkv1"):
        wr = P.sb([128, 8, 1024], BF16)
        wk = P.sb([128, 8, 1024], BF16)
        wv = P.sb([128, 8, 1024], BF16)
        for w, nm in ((wr, "rwkv_wr"), (wk, "rwkv_wk"), (wv, "rwkv_wv")):
            P.dma("gpsimd", w[:], fm(io[nm]), writes=[nm], sem=nm)
        lw1 = P.sb([128, 8, 128], BF16)
        la1 = P.sb([128, 8, 128], BF16)
        g1 = P.sb([128, 8, 128], BF16)
        for d in range(2):
            P.dma("gpsimd", lw1[:, :, d * 64:(d + 1) * 64], io["rwkv_w1"][d].rearrange("(c p) j -> p c j", p=128), writes=["lw1"], sem=f"lw1{d}")
            P.dma("gpsimd", la1[:, :, d * 64:(d + 1) * 64], io["rwkv_a1"][d].rearrange("(c p) j -> p c j", p=128), writes=["la1"], sem=f"la1{d}")
        P.dma("gpsimd", g1[:], io["rwkv_g1"].rearrange("(c p) j -> p c j", p=128), writes=["g1"], sem="g1")
        w2s = P.sb([128, 1024], BF16)
        a2s = P.sb([128, 1024], BF16)
        g2 = P.sb([128, 1024], BF16)
        P.dma("gpsimd", w2s[:], io["rwkv_w2"].rearrange("d j f -> (d j) f"), writes=["w2s"], sem="w2s")
        P.dma("gpsimd", a2s[:], io["rwkv_a2"].rearrange("d j f -> (d j) f"), writes=["a2s"], sem="a2s")
        P.dma("gpsimd", g2[:], io["rwkv_g2"], writes=["g2"], sem="g2")

        hh = P.sb([128, 8, 384], F32)
        xx = P.sb([128, 8, 256], F32)
        xr = P.sb([128, 8, 256], BF16)
        xk = P.sb([128, 8, 256], BF16)
        xv = P.sb([128, 8, 256], BF16)
        xrot = P.sb([128, 8, 256], BF16)
        lwt = P.sb([128, 256], BF16)
        lat = P.sb([128, 256], BF16)
        sg = P.sb([128, 256], BF16)
        f32t = {}
        for nm in ("r", "k", "sw0", "sw1", "ag0", "ag1", "kq", "lnv", "rs", "kkn", "fac", "kd0", "kd1", "b0", "b1",
                   "L", "Lx", "Lb", "E1", "E2", "E3", "ks"):
            f32t[nm] = P.sb([128, 256], F32, "t_" + nm)
        sqb = P.sb([128, 256], BF16)
        RK = P.sb([128, 4, 2, 64], BF16)
        VTbd = P.sb([128, 4, 128], F32)
        GTbd = P.sb([128, 4, 128], F32)
        Vf = P.sb([128, 4, 64], F32)
        Gf = P.sb([128, 4, 64], F32)
        YPs = P.sb([128, 4, 64], F32)
        SAs = P.sb([128, 4, 64], F32)
        gamb_t = P.sb([128, 8, 4], F32)
        bon_t = P.sb([128, 8, 4], F32)
        Sf = P.sb([128, 8, 64], BF16)
        ARq = [[P.sb([128, 4, 2, 128], BF16, f"AR{q}{d}") for d in range(2)] for q in range(2)]
        KTq = [[P.sb([128, 4, 128], BF16, f"KT{q}{d}") for d in range(2)] for q in range(2)]
        BTq = [[P.sb([128, 4, 128], BF16, f"BT{q}{d}") for d in range(2)] for q in range(2)]
        Vbq = [P.sb([128, 4, 64], BF16, f"Vb{q}") for q in range(3)]
        gamq = [[P.sb([128, 4], F32, f"gam{q}{d}") for d in range(2)] for q in range(3)]
        inv = []
        for d in range(2):
            st = {}
            for nm, shp in (("Atok", [128, 4, 128]), ("Btok", [128, 4, 128]), ("MQ", [128, 4, 256]), ("MWa", [128, 4, 2, 128]),
                            ("MWb", [128, 4, 2, 128]), ("MTa", [128, 4, 128]), ("MTb", [128, 4, 128])):
                st[nm] = P.sb(shp, BF16, f"i{d}_{nm}")
            inv.append(st)
        fin = []
        for q in range(2):
            row = []
            for d in range(2):
                st = {}
                for nm, shp in (("Ktok", [128, 4, 128]), ("NP", [128, 4, 256]), ("XW", [128, 4, 256]), ("NVb", [128, 4, 64]),
                                ("GY", [128, 4, 128]), ("GS", [128, 4, 128])):
                    st[nm] = P.sb(shp, BF16, f"f{q}{d}_{nm}")
                row.append(st)
            fin.append(row)
        ppt = [P.ps([128, 512], F32) for _ in range(2)]
        pp = [t_[:, 0:256] for t_ in ppt]
        pf = P.ps([128, 512], F32)
        pb = [P.ps([128, 512], F32) for _ in range(5)]
        cnt = {"pp": 0, "pb": 0}
        nmod = {"pp": 2, "pb": 5}

        def nxt(kind):
            i = cnt[kind] % nmod[kind]
            cnt[kind] += 1
            return i

        for q in range(2):
            for d in range(2):
                P.memset("gpsimd", ARq[q][d][:], 0.0, [f"AR{q}{d}"])
                P.memset("gpsimd", KTq[q][d][:], 0.0, [f"KT{q}{d}"])
                P.memset("gpsimd", BTq[q][d][:], 0.0, [f"BT{q}{d}"])
        P.memset("gpsimd", RK[:], 0.0, ["RK"])
        P.memset("gpsimd", VTbd[:], 0.0, ["VTbd"])
        P.memset("gpsimd", GTbd[:], 0.0, ["GTbd"])
        P.memset("gpsimd", Sf[:], 0.0, [("Sf", p) for p in range(8)])

        def v3(ap):
            return ap.rearrange("p (u s) -> p u s", s=64)

        def u128(ap):
            return ap.rearrange("p (u x) -> p u x", x=128)

        def load_hh(ti):
            isctx, idx = RW_ORDER1[ti]
            off = 4288 if isctx else 64 + 256 * idx
            P.dma("sync", hh[:], fm(hp[:, off - 64: off + 320]), writes=["hh"], sem="hh")

        def proj8(w_cols_fn, xb, bn, extra_r):
            i = nxt("pp")
            for c in range(8):
                P.mm(pp[i], w_cols_fn(c), xb[:, c, :], c == 0, c == 7, [(bn, c)] + extra_r, [f"pp{i}"])
            return i

        def tprep(ti):
            isctx, idx = RW_ORDER1[ti]
            hc = hh[:, :, 64:320]
            XXW = [("xx", c) for c in range(8)]
            if not isctx:
                h4 = hh[:, :, 64:320].rearrange("p c (r w) -> p c r w", w=64)
                x4 = xx[:].rearrange("p c (r w) -> p c r w", w=64)
                P.tt("vector", x4[:, 0:2, :, 1:64], h4[:, 0:2, :, 0:63], h4[:, 0:2, :, 1:64], ALU.subtract, ["hh"], XXW[0:2])
                P.ts("gpsimd", x4[:, 0:2, :, 0:1], h4[:, 0:2, :, 0:1], -1.0, 0.0, ALU.mult, ALU.add, ["hh"], [("xxe", 0)])
                P.tt("vector", x4[:, 2:4, :, 0:63], h4[:, 2:4, :, 1:64], h4[:, 2:4, :, 0:63], ALU.subtract, ["hh"], XXW[2:4])
                P.ts("gpsimd", x4[:, 2:4, :, 63:64], h4[:, 2:4, :, 63:64], -1.0, 0.0, ALU.mult, ALU.add, ["hh"], [("xxe", 1)])
                P.tt("gpsimd", xx[:, 4:6, :], hh[:, 4:6, 0:256], hh[:, 4:6, 64:320], ALU.subtract, ["hh"], XXW[4:6])
                P.tt("gpsimd", xx[:, 6:8, :], hh[:, 6:8, 128:384], hh[:, 6:8, 64:320], ALU.subtract, ["hh"], XXW[6:8])
            else:
                P.tt("vector", xx[:, 0:4, :], hh[:, 0:4, 63:319], hh[:, 0:4, 64:320], ALU.subtract, ["hh"], XXW[0:4] + [("xxe", 0)])
                P.tt("gpsimd", xx[:, 4:8, :], hh[:, 4:8, 65:321], hh[:, 4:8, 64:320], ALU.subtract, ["hh"], XXW[4:8] + [("xxe", 1)])
            yield

            def mk_xj(j, buf, bn):
                for c in range(8):
                    P.stt(buf[:, c, :], xx[:, c, :], vec[:, 5 + j, c:c + 1], hc[:, c, :], ALU.mult, ALU.add,
                          [("xx", c), ("xxe", 0), ("xxe", 1), "hh", "vec"], [(bn, c)])

            mk_xj(1, xrot, "xrot")
            yield
            i = proj8(lambda c: lw1[:, c, :], xrot, "xrot", ["lw1"])
            P.act(lwt[:], pp[i], AF.Tanh, [f"pp{i}"], ["lwt"])
            yield
            mk_xj(4, xrot, "xrot")
            yield
            i = proj8(lambda c: la1[:, c, :], xrot, "xrot", ["la1"])
            P.cp("scalar", lat[:], pp[i], [f"pp{i}"], ["lat"])
            yield
            mk_xj(5, xrot, "xrot")
            yield
            i = proj8(lambda c: g1[:, c, :], xrot, "xrot", ["g1"])
            P.act(sg[:], pp[i], AF.Sigmoid, [f"pp{i}"], ["sg"])
            yield
            mk_xj(0, xr, "xr")
            yield
            mk_xj(2, xk, "xk")
            yield
            mk_xj(3, xv, "xv")
            if ti + 1 < len(RW_ORDER1):
                load_hh(ti + 1)
            yield

        def prep(ti, oc, q, z):
            isctx, idx = RW_ORDER1[ti]
            tg = 16 if isctx else idx
            cs = slice(oc * 128, (oc + 1) * 128)
            t = f32t
            AR, KT, BT, Vb, gam = ARq[q], KTq[q], BTq[q], Vbq[z], gamq[z]
            i = proj8(lambda c: wr[:, c, cs], xr, "xr", ["rwkv_wr"])
            P.cp("scalar", t["r"][:], pp[i], [f"pp{i}"], ["r"])
            i = proj8(lambda c: wk[:, c, cs], xk, "xk", ["rwkv_wk"])
            P.cp("scalar", t["k"][:], pp[i], [f"pp{i}"], ["k"])
            i = proj8(lambda c: wv[:, c, cs], xv, "xv", ["rwkv_wv"])
            vt4 = VTbd[:].rearrange("p u (h s) -> p u h s", h=2)
            for h2 in range(2):
                sl = slice(h2 * 64, (h2 + 1) * 64)
                P.cp("scalar", vt4[sl, :, h2, :], v3(pp[i][sl, :]), [f"pp{i}"], ["VTbd"])
            i = nxt("pp")
            P.mm(pp[i], g2[:, cs], sg[:], True, True, ["g2", "sg"], [f"pp{i}"])
            gt4 = GTbd[:].rearrange("p u (h s) -> p u h s", h=2)
            for h2 in range(2):
                sl = slice(h2 * 64, (h2 + 1) * 64)
                P.cp("scalar", gt4[sl, :, h2, :], v3(pp[i][sl, :]), [f"pp{i}"], ["GTbd"])
            yield
            j = nxt("pb")
            for u in range(4):
                P.tr(pb[j][:, u * 128:(u + 1) * 128], VTbd[:, u, :], identf[:], ["VTbd", "identf"], [f"pb{j}"])
            pv = u128(pb[j][:])
            for h2 in range(2):
                sl = slice(h2 * 64, (h2 + 1) * 64)
                P.cp("scalar", Vf[sl, :, :], pv[sl, :, h2 * 64:(h2 + 1) * 64], [f"pb{j}"], ["Vf"])
            P.cp("gpsimd", Vb[:], Vf[:], ["Vf"], [f"Vb{z}"])
            P.dma("sync", S["vst"][tg, oc].rearrange("p (u s) -> p u s", s=64), Vf[:], reads=["Vf"], writes=[("vst", tg, oc)], sem="Vf")
            j = nxt("pb")
            for u in range(4):
                P.tr(pb[j][:, u * 128:(u + 1) * 128], GTbd[:, u, :], identf[:], ["GTbd", "identf"], [f"pb{j}"])
            pv = u128(pb[j][:])
            for h2 in range(2):
                sl = slice(h2 * 64, (h2 + 1) * 64)
                P.cp("scalar", Gf[sl, :, :], pv[sl, :, h2 * 64:(h2 + 1) * 64], [f"pb{j}"], ["Gf"])
            P.dma("sync", S["gst"][tg, oc].rearrange("p (u s) -> p u s", s=64), Gf[:], reads=["Gf"], writes=[("gst", tg, oc)], sem="Gf")
            yield
            for d in range(2):
                dl = slice(d * 64, (d + 1) * 64)
                i = nxt("pp")
                P.mm(pp[i], w2s[dl, cs], lwt[dl, :], True, True, ["w2s", "lwt"], [f"pp{i}"])
                P.act(t[f"sw{d}"][:], pp[i], AF.Sigmoid, [f"pp{i}", "vec"], [f"sw{d}"], bias=vec[:, 11 + d, oc:oc + 1])
                i = nxt("pp")
                P.mm(pp[i], a2s[dl, cs], lat[dl, :], True, True, ["a2s", "lat"], [f"pp{i}"])
                P.act(t[f"ag{d}"][:], pp[i], AF.Sigmoid, [f"pp{i}", "vec"], [f"ag{d}"], bias=vec[:, 13 + d, oc:oc + 1])
            yield
            P.ts("vector", t["kq"][:], t["k"][:], vec[:, 15, oc:oc + 1], None, ALU.mult, None, ["k", "vec"], ["kq"])
            P.act(sqb[:], t["kq"][:], AF.Square, ["kq"], ["sqb"])
            i = nxt("pp")
            P.mm(pp[i], bones[:], sqb[:], True, True, ["bones", "sqb"], [f"pp{i}"])
            P.act(t["lnv"][:], pp[i], AF.Ln, [f"pp{i}"], ["lnv"], bias=1e-12)
            P.act(t["rs"][:], t["lnv"][:], AF.Exp, ["lnv"], ["rs"], scale=-0.5)
            P.tt("gpsimd", t["kkn"][:], t["kq"][:], t["rs"][:], ALU.mult, ["kq", "rs"], ["kkn"])
            for d in range(2):
                sw, ag, kd, bb = t[f"sw{d}"], t[f"ag{d}"], t[f"kd{d}"], t[f"b{d}"]
                EE = "gpsimd" if d == 0 else "vector"
                P.ts(EE, t["fac"][:], ag[:], vec[:, 16, oc:oc + 1], vec[:, 17, oc:oc + 1], ALU.mult, ALU.add, [f"ag{d}", "vec"], ["fac"])
                P.tt(EE, kd[:], t["k"][:], t["fac"][:], ALU.mult, ["k", "fac"], [f"kd{d}"])
                P.tt(EE, bb[:], t["kkn"][:], ag[:], ALU.mult, ["kkn", f"ag{d}"], [f"b{d}"])
                P.op("vector", lambda e, sw=sw: e.tensor_tensor_scan(out=t["L"][:], data0=rmask[:], data1=sw[:], initial=0.0,
                                                                      op0=ALU.mult, op1=ALU.add), [f"sw{d}", "rmask"], ["L"])
                L3 = v3(t["L"][:])
                if d == 0:
                    P.tt(EE, t["Lx"][:], t["L"][:], sw[:], ALU.subtract, ["L", f"sw{d}"], ["Lx"])
                    Li, Lin = t["L"], "L"
                else:
                    P.tt(EE, v3(t["Lx"][:]), L3[:, :, 63:64].broadcast_to([128, 4, 64]), L3, ALU.subtract, ["L"], ["Lx"])
                    P.tt(EE, t["Lb"][:], t["Lx"][:], sw[:], ALU.add, ["Lx", f"sw{d}"], ["Lb"])
                    Li, Lin = t["Lb"], "Lb"
                P.act(t["E1"][:], Li[:], AF.Exp, [Lin], ["E1"], scale=-C0)
                P.act(t["E3"][:], Li[:], AF.Exp, [Lin], ["E3"], scale=C0)
                P.act(t["E2"][:], t["Lx"][:], AF.Exp, ["Lx"], ["E2"], scale=-C0)
                ar5 = AR[d][:].rearrange("p u a (h s) -> p u a h s", h=2)
                kt4 = KT[d][:].rearrange("p u (h s) -> p u h s", h=2)
                bt4 = BT[d][:].rearrange("p u (h s) -> p u h s", h=2)
                for h2 in range(2):
                    sl = slice(h2 * 64, (h2 + 1) * 64)
                    P.stt(ar5[sl, :, 0, h2, :], v3(t["kkn"][sl, :]), -1.0, v3(t["E2"][sl, :]), ALU.mult, ALU.mult, ["kkn", "E2"], [f"AR{q}{d}"])
                    P.tt(EE, ar5[sl, :, 1, h2, :], v3(t["r"][sl, :]), v3(t["E1"][sl, :]), ALU.mult, ["r", "E1"], [f"AR{q}{d}"])
                    P.tt(EE, kt4[sl, :, h2, :], v3(kd[sl, :]), v3(t["E3"][sl, :]), ALU.mult, [f"kd{d}", "E3"], [f"KT{q}{d}"])
                    P.tt(EE, bt4[sl, :, h2, :], v3(bb[sl, :]), v3(t["E3"][sl, :]), ALU.mult, [f"b{d}", "E3"], [f"BT{q}{d}"])
                E13 = v3(t["E1"][:])
                gsrc = E13[:, :, 63] if d == 0 else E13[:, :, 0]
                P.cp("vector", gam[d][:], gsrc, ["E1"], [f"gam{z}{d}"])
                if d == 1:
                    P.cp("gpsimd", gamb_t[:, oc, :], gam[1][:], [f"gam{z}1"], ["gamb_t"])
                yield
            P.tt("gpsimd", t["ks"][:], t["kd0"][:], t["kd1"][:], ALU.add, ["kd0", "kd1"], ["ks"])
            for h2 in range(2):
                sl = slice(h2 * 64, (h2 + 1) * 64)
                P.stt(RK[sl, :, h2, :], v3(t["r"][sl, :]), vec[sl, 18, oc:oc + 1], v3(t["ks"][sl, :]), ALU.mult, ALU.mult, ["r", "ks", "vec"], ["RK"])
            i = nxt("pp")
            for u in range(4):
                P.mm(pp[i][:, u:u + 1], RK[:, u, :, :].rearrange("p h s -> p (h s)"), onesb[:, 0:1], True, True, ["RK", "onesb"], [f"pp{i}"])
            P.cp("scalar", bon_t[:, oc, :], pp[i][:, 0:4], [f"pp{i}"], ["bon_t"])
            if oc == 7:
                P.dma("sync", S["gamb"][tg], gamb_t[:].rearrange("p a b -> p (a b)"), reads=["gamb_t"], writes=[("gamb", tg)], sem="gamb_t")
                P.dma("sync", S["bon"][tg], bon_t[:].rearrange("p a b -> p (a b)"), reads=["bon_t"], writes=[("bon", tg)], sem="bon_t")
            yield

        def chain(ti, oc, q, d, z):
            AR, KT, BT, Vb = ARq[q][d], KTq[q][d], BTq[q][d], Vbq[z]
            ARn, KTn, BTn, Vbn = f"AR{q}{d}", f"KT{q}{d}", f"BT{q}{d}", f"Vb{z}"
            iv, fn = inv[d], fin[q][d]
            IR = lambda nm: f"i{d}_{nm}"
            FR = lambda nm: f"f{q}{d}_{nm}"
            mS, mC = (0, 2) if d == 0 else (2, 0)
            mSI = masks[:, mS:mS + 2, :].rearrange("p a b -> p (a b)").unsqueeze(1).broadcast_to([128, 4, 256])
            mCb = masks[:, mC, :].unsqueeze(1).broadcast_to([128, 4, 128])
            idb = identb[:].unsqueeze(1).broadcast_to([128, 4, 128])
            for src, srcn, dst, dstn in ((AR[:, :, 0, :], ARn, iv["Atok"], IR("Atok")), (BT[:], BTn, iv["Btok"], IR("Btok")),
                                         (KT[:], KTn, fn["Ktok"], FR("Ktok"))):
                j = nxt("pb")
                pbt = pb[j][:].bitcast(BF16)
                for u in range(4):
                    P.tr(pbt[:, u * 128:(u + 1) * 128], src[:, u, :], identb[:], [srcn, "identb"], [f"pb{j}"])
                P.cp("scalar", dst[:].rearrange("p u x -> p (u x)"), pbt[:, 0:512], [f"pb{j}"], [dstn])
            mSb = masks[:, mS, :].unsqueeze(1).broadcast_to([128, 4, 128])
            mIb = masks[:, mS + 1, :].unsqueeze(1).broadcast_to([128, 4, 128])

            def two_bank(mm_fn):
                j0, j1 = nxt("pb"), nxt("pb")
                for u in range(4):
                    mm_fn(u, pb[j0][:, u * 128:(u + 1) * 128], f"pb{j0}", pb[j1][:, u * 128:(u + 1) * 128], f"pb{j1}")
                return j0, j1

            for lhs, lhsn, dst, dstn in ((BT, BTn, iv["MQ"], IR("MQ")), (KT, KTn, fn["NP"], FR("NP"))):
                def mm_ab(u, o0, n0, o1, n1, lhs=lhs, lhsn=lhsn):
                    P.mm(o0, lhs[:, u, :], AR[:, u, 0, :], True, True, [lhsn, ARn], [n0])
                    P.mm(o1, lhs[:, u, :], AR[:, u, 1, :], True, True, [lhsn, ARn], [n1])
                j0, j1 = two_bank(mm_ab)
                P.tt("vector", dst[:, :, 0:128], u128(pb[j0][:]), mSb, ALU.mult, [f"pb{j0}", "masks"], [dstn])
                P.tt("vector", dst[:, :, 128:256], u128(pb[j1][:]), mIb, ALU.mult, [f"pb{j1}", "masks"], [dstn])
            j = nxt("pb")
            for u in range(4):
                P.mm(pb[j][:, u * 128:(u + 1) * 128], AR[:, u, 0, :], BT[:, u, :], True, True, [ARn, BTn], [f"pb{j}"])
            cur, curn, nx, nxn = iv["MWa"], IR("MWa"), iv["MWb"], IR("MWb")
            P.tt("vector", cur[:, :, 0, :], u128(pb[j][:]), mCb, ALU.mult, [f"pb{j}", "masks"], [curn])
            yield
            j = nxt("pb")
            for u in range(4):
                P.mm(pb[j][:, u * 128:(u + 1) * 128], iv["MQ"][:, u, 0:128], cur[:, u, 0, :], True, True, [IR("MQ"), curn], [f"pb{j}"])
            P.cp("scalar", nx[:, :, 0, :], u128(pb[j][:]), [f"pb{j}"], [nxn])
            P.tt("gpsimd", nx[:, :, 1, :], cur[:, :, 0, :], idb, ALU.add, [curn, "identb"], [nxn])
            j = nxt("pb")
            for u in range(4):
                P.mm(pb[j][:, u * 128:(u + 1) * 128], cur[:, u, 0, :], iv["MQ"][:, u, 0:128], True, True, [IR("MQ"), curn], [f"pb{j}"])
            curT, curTn, nxT, nxTn = iv["MTa"], IR("MTa"), iv["MTb"], IR("MTb")
            P.cp("scalar", curT[:], u128(pb[j][:]), [f"pb{j}"], [curTn])
            cur, curn, nx, nxn = nx, nxn, cur, curn
            yield
            for lev in range(1, 5):
                def mm_lev(u, o0, n0, o1, n1, cur=cur, curn=curn, curT=curT, curTn=curTn):
                    P.mm(o0, curT[:, u, :], cur[:, u, 0, :], True, True, [curTn, curn], [n0])
                    P.mm(o1, curT[:, u, :], cur[:, u, 1, :], True, True, [curTn, curn], [n1])
                j0, j1 = two_bank(mm_lev)
                P.cp("scalar", nx[:, :, 0, :], u128(pb[j0][:]), [f"pb{j0}"], [nxn])
                P.tt("vector", nx[:, :, 1, :], u128(pb[j1][:]), cur[:, :, 1, :], ALU.add, [f"pb{j1}", curn], [nxn])
                j = nxt("pb")
                for u in range(4):
                    P.mm(pb[j][:, u * 128:(u + 1) * 128], cur[:, u, 0, :], curT[:, u, :], True, True, [curn, curTn], [f"pb{j}"])
                P.cp("scalar", nxT[:], u128(pb[j][:]), [f"pb{j}"], [nxTn])
                cur, curn, nx, nxn = nx, nxn, cur, curn
                curT, curTn, nxT, nxTn = nxT, nxTn, curT, curTn
                yield
            j = nxt("pb")
            for u in range(4):
                P.mm(pb[j][:, u * 128:(u + 1) * 128], curT[:, u, :], cur[:, u, 1, :], True, True, [curTn, curn], [f"pb{j}"])
            P.tt("vector", nx[:, :, 1, :], u128(pb[j][:]), cur[:, :, 1, :], ALU.add, [f"pb{j}", curn], [nxn])
            W6, W6n = nx, nxn
            j = nxt("pb")
            for u in range(4):
                P.mm(pb[j][:, u * 64:(u + 1) * 64], fn["NP"][:, u, 0:128], Vb[:, u, :], True, True, [FR("NP"), Vbn], [f"pb{j}"])
            P.cp("scalar", fn["NVb"][:].rearrange("p u x -> p (u x)"), pb[j][:, 0:256], [f"pb{j}"], [FR("NVb")])
            yield

            def mm_d(u, o0, n0, o1, n1):
                P.mm(o0, W6[:, u, 1, :], iv["MQ"][:, u, 128:256], True, True, [W6n, IR("MQ")], [n0])
                P.mm(o1, W6[:, u, 1, :], iv["Btok"][:, u, :], True, True, [W6n, IR("Btok")], [n1])
            j0, j1 = two_bank(mm_d)
            P.cp("scalar", fn["XW"][:, :, 0:128], u128(pb[j0][:]), [f"pb{j0}"], [FR("XW")])
            P.cp("vector", fn["XW"][:, :, 128:256], u128(pb[j1][:]), [f"pb{j1}"], [FR("XW")])
            yield

            def mm_f(u, o0, n0, o1, n1):
                P.mm(o0, iv["Atok"][:, u, :], fn["XW"][:, u, 0:128], True, True, [IR("Atok"), FR("XW")], [n0])
                P.mm(o1, iv["Atok"][:, u, :], fn["XW"][:, u, 128:256], True, True, [IR("Atok"), FR("XW")], [n1])
            j0, j1 = two_bank(mm_f)
            P.tt("vector", fn["GY"][:], u128(pb[j0][:]), AR[:, :, 1, :], ALU.add, [f"pb{j0}", ARn], [FR("GY")])
            P.tt("vector", fn["GS"][:], u128(pb[j1][:]), idb, ALU.add, [f"pb{j1}", "identb"], [FR("GS")])
            yield

        def finish(ti, oc, q, z):
            isctx, idx = RW_ORDER1[ti]
            tg = 16 if isctx else idx
            sf, sb_ = fin[q]
            F0 = lambda nm: f"f{q}0_{nm}"
            F1 = lambda nm: f"f{q}1_{nm}"
            Vb, Vbn, gam = Vbq[z], f"Vb{z}", gamq[z]
            SFR = ("Sf", oc)
            for u in range(4):
                yo = pf[:, u * 64:(u + 1) * 64]
                P.mm(yo, sf["NP"][:, u, 128:256], Vb[:, u, :], True, False, [F0("NP"), Vbn], ["pf"])
                P.mm(yo, sf["XW"][:, u, 0:128], sf["NVb"][:, u, :], False, False, [F0("XW"), F0("NVb")], ["pf"])
                P.mm(yo, sb_["NP"][:, u, 128:256], Vb[:, u, :], False, False, [F1("NP"), Vbn], ["pf"])
                P.mm(yo, sb_["XW"][:, u, 0:128], sb_["NVb"][:, u, :], False, False, [F1("XW"), F1("NVb")], ["pf"])
                P.mm(yo, sf["GY"][:, u, :], Sf[:, oc, :], False, True, [F0("GY"), SFR], ["pf"])
                so = pf[:, 256:320]
                P.mm(so, sf["Ktok"][:, u, :], Vb[:, u, :], True, False, [F0("Ktok"), Vbn], ["pf"])
                P.mm(so, sf["XW"][:, u, 128:256], sf["NVb"][:, u, :], False, False, [F0("XW"), F0("NVb")], ["pf"])
                P.mm(so, sf["GS"][:, u, :], Sf[:, oc, :], False, True, [F0("GS"), SFR], ["pf"])
                P.ts("vector", Sf[:, oc, :], so, gam[0][:, u:u + 1], None, ALU.mult, None, ["pf", f"gam{z}0"], [SFR])
                yield
            P.cp("vector", YPs[:].rearrange("p u x -> p (u x)"), pf[:, 0:256], ["pf"], ["YPs"])
            P.dma("sync", S["yp"][tg, oc], YPs[:].rearrange("p u x -> p (u x)"), reads=["YPs"], writes=[("yp", tg, oc)], sem="YPs")
            j = nxt("pb")
            for u in range(4):
                so = pb[j][:, u * 64:(u + 1) * 64]
                P.mm(so, sb_["Ktok"][:, u, :], Vb[:, u, :], True, False, [F1("Ktok"), Vbn], [f"pb{j}"])
                P.mm(so, sb_["XW"][:, u, 128:256], sb_["NVb"][:, u, :], False, True, [F1("XW"), F1("NVb")], [f"pb{j}"])
            P.cp("scalar", SAs[:].rearrange("p u x -> p (u x)"), pb[j][:, 0:256], [f"pb{j}"], ["SAs"])
            P.dma("sync", S["sadd"][tg, oc], SAs[:].rearrange("p u x -> p (u x)"), reads=["SAs"], writes=[("sadd", tg, oc)], sem="SAs")
            P.dma("sync", S["gyb"][tg, oc], sb_["GY"][:].rearrange("p u x -> p (u x)"), reads=[F1("GY")], writes=[("gyb", tg, oc)], sem=F1("GY"))
            P.dma("sync", S["gsb"][tg, oc], sb_["GS"][:].rearrange("p u x -> p (u x)"), reads=[F1("GS")], writes=[("gsb", tg, oc)], sem=F1("GS"))
            yield

        NT = len(RW_ORDER1)
        NJ = NT * 8
        done = {"prep": set(), "c0": set(), "c1": set(), "fin": set(), "tprep": set()}

        def stream_P():
            for ti in range(NT):
                yield ("tprep", ti, lambda ti=ti: (ti == 0 or ("prep", (ti - 1) * 8 + 7) in donef), lambda ti=ti: tprep(ti))
                for oc in range(8):
                    k = ti * 8 + oc
                    yield ("prep", k, lambda k=k: ((k < 2 or (("c0", k - 2) in donef and ("c1", k - 2) in donef)) and (k < 3 or ("fin", k - 3) in donef)),
                           lambda ti=ti, oc=oc, k=k: prep(ti, oc, k % 2, k % 3))

        def stream_C(d):
            for k in range(NJ):
                ti, oc = divmod(k, 8)
                yield (f"c{d}", k, lambda k=k: (("prep", k) in donef and (k < 2 or ("fin", k - 2) in donef)),
                       lambda ti=ti, oc=oc, k=k: chain(ti, oc, k % 2, d, k % 3))

        def stream_F():
            for k in range(NJ):
                ti, oc = divmod(k, 8)
                yield ("fin", k, lambda k=k: (("c0", k) in donef and ("c1", k) in donef),
                       lambda ti=ti, oc=oc, k=k: finish(ti, oc, k % 2, k % 3))

        donef = set()
        load_hh(0)
        streams = [stream_C(0), stream_C(1), stream_F(), stream_P()]
        cur = [None] * 4
        pend = [None] * 4
        alive = [True] * 4
        while any(alive):
            progressed = False
            for si in range(4):
                if not alive[si]:
                    continue
                if cur[si] is None:
                    if pend[si] is None:
                        try:
                            pend[si] = next(streams[si])
                        except StopIteration:
                            alive[si] = False
                            continue
                    kind, k, ready, mk = pend[si]
                    if not ready():
                        continue
                    cur[si] = (kind, k, mk())
                    pend[si] = None
                kind, k, gen = cur[si]
                try:
                    next(gen)
                    progressed = True
                except StopIteration:
                    donef.add((kind, k))
                    cur[si] = None
                    progressed = True
            assert progressed or not any(alive), "scheduler stuck"


def stage_rwkv1a(P, io, G, hp, S):
    vec, masks, identb, identf, bones, onesb, rmask = (G[k] for k in ("vec", "masks", "identb", "identf", "bones", "onesb", "rmask"))
    with P.phase("rwkv1a"):
        wr = P.sb([128, 8, 1024], BF16)
        wk = P.sb([128, 8, 1024], BF16)
        wv = P.sb([128, 8, 1024], BF16)
        for w, nm in ((wr, "rwkv_wr"), (wk, "rwkv_wk"), (wv, "rwkv_wv")):
            P.dma("gpsimd", w[:], fm(io[nm]), writes=[nm], sem=nm)
        lw1 = P.sb([128, 8, 128], BF16)
        la1 = P.sb([128, 8, 128], BF16)
        g1 = P.sb([128, 8, 128], BF16)
        for d in range(2):
            P.dma("gpsimd", lw1[:, :, d * 64:(d + 1) * 64], io["rwkv_w1"][d].rearrange("(c p) j -> p c j", p=128), writes=["lw1"], sem=f"lw1{d}")
            P.dma("gpsimd", la1[:, :, d * 64:(d + 1) * 64], io["rwkv_a1"][d].rearrange("(c p) j -> p c j", p=128), writes=["la1"], sem=f"la1{d}")
        P.dma("gpsimd", g1[:], io["rwkv_g1"].rearrange("(c p) j -> p c j", p=128), writes=["g1"], sem="g1")
        w2s = P.sb([128, 1024], BF16)
        a2s = P.sb([128, 1024], BF16)
        g2 = P.sb([128, 1024], BF16)
        P.dma("gpsimd", w2s[:], io["rwkv_w2"].rearrange("d j f -> (d j) f"), writes=["w2s"], sem="w2s")
        P.dma("gpsimd", a2s[:], io["rwkv_a2"].rearrange("d j f -> (d j) f"), writes=["a2s"], sem="a2s")
        P.dma("gpsimd", g2[:], io["rwkv_g2"], writes=["g2"], sem="g2")

        hh = P.sb([128, 8, 384], F32)
        xx = P.sb([128, 8, 256], F32)
        xr = P.sb([128, 8, 256], BF16)
        xk = P.sb([128, 8, 256], BF16)
        xv = P.sb([128, 8, 256], BF16)
        xrot = P.sb([128, 8, 256], BF16)
        lwt = P.sb([128, 256], BF16)
        lat = P.sb([128, 256], BF16)
        sg = P.sb([128, 256], BF16)
        NSET = 3
        bufs = []
        for w_ in range(NSET):
            B_ = {"t": {}}
            for nm in ("r", "k", "sw0", "sw1", "ag0", "ag1", "kq", "lnv", "rs", "kkn", "fac", "kd0", "kd1", "b0", "b1",
                       "L", "Lx", "Lb", "E1", "E2", "E3", "ks"):
                B_["t"][nm] = P.sb([128, 256], F32, f"t{w_}_" + nm)
            B_["sqb"] = P.sb([128, 256], BF16)
            B_["RK"] = P.sb([128, 4, 2, 64], BF16)
            B_["VTbd"] = P.sb([128, 4, 128], F32)
            B_["GTbd"] = P.sb([128, 4, 128], F32)
            B_["Vf"] = P.sb([128, 4, 64], F32)
            B_["Gf"] = P.sb([128, 4, 64], F32)
            B_["ops"] = P.sb([128, 2, 4, 256], BF16)
            B_["vb"] = P.sb([128, 4, 64], BF16)
            B_["gam"] = P.sb([128, 2, 4], F32)
            bufs.append(B_)
            P.memset("gpsimd", B_["RK"][:], 0.0, [("RK", w_)])
            P.memset("gpsimd", B_["VTbd"][:], 0.0, [("VTbd", w_)])
            P.memset("gpsimd", B_["GTbd"][:], 0.0, [("GTbd", w_)])
        P.ns_set = frozenset(["r", "k", "sw0", "sw1", "ag0", "ag1", "kq", "lnv", "rs", "kkn", "fac", "kd0", "kd1", "b0", "b1",
                              "L", "Lx", "Lb", "E1", "E2", "E3", "ks", "sqb", "RK", "VTbd", "GTbd", "Vf", "Gf", "ops_st", "vb_st", "gam_st"])
        gamb_t = P.sb([128, 8, 4], F32)
        bon_t = P.sb([128, 8, 4], F32)
        ppt = [P.ps([128, 512], F32) for _ in range(4)]
        pp = [t_[:, 0:256] for t_ in ppt]
        pb = [P.ps([128, 512], F32) for _ in range(4)]
        cnt = {"pp": 0, "pb": 0}
        nmod = {"pp": 4, "pb": 4}

        def nxt(kind):
            i = cnt[kind] % nmod[kind]
            cnt[kind] += 1
            return i

        def v3(ap):
            return ap.rearrange("p (u s) -> p u s", s=64)

        def u128(ap):
            return ap.rearrange("p (u x) -> p u x", x=128)

        def load_hh(ti):
            isctx, idx = RW_ORDER1[ti]
            off = 4288 if isctx else 64 + 256 * idx
            P.dma("sync", hh[:], fm(hp[:, off - 64: off + 320]), writes=["hh"], sem="hh")

        def proj8(w_cols_fn, xb, bn, extra_r):
            i = nxt("pp")
            for c in range(8):
                P.mm(pp[i], w_cols_fn(c), xb[:, c, :], c == 0, c == 7, [(bn, c)] + extra_r, [f"pp{i}"])
            return i

        def tprep(ti):
            isctx, idx = RW_ORDER1[ti]
            hc = hh[:, :, 64:320]
            XXW = [("xx", c) for c in range(8)]
            if not isctx:
                h4 = hh[:, :, 64:320].rearrange("p c (r w) -> p c r w", w=64)
                x4 = xx[:].rearrange("p c (r w) -> p c r w", w=64)
                P.tt("vector", x4[:, 0:2, :, 1:64], h4[:, 0:2, :, 0:63], h4[:, 0:2, :, 1:64], ALU.subtract, ["hh"], XXW[0:2])
                P.ts("gpsimd", x4[:, 0:2, :, 0:1], h4[:, 0:2, :, 0:1], -1.0, 0.0, ALU.mult, ALU.add, ["hh"], [("xxe", 0)])
                P.tt("vector", x4[:, 2:4, :, 0:63], h4[:, 2:4, :, 1:64], h4[:, 2:4, :, 0:63], ALU.subtract, ["hh"], XXW[2:4])
                P.ts("gpsimd", x4[:, 2:4, :, 63:64], h4[:, 2:4, :, 63:64], -1.0, 0.0, ALU.mult, ALU.add, ["hh"], [("xxe", 1)])
                P.tt("gpsimd", xx[:, 4:6, :], hh[:, 4:6, 0:256], hh[:, 4:6, 64:320], ALU.subtract, ["hh"], XXW[4:6])
                P.tt("gpsimd", xx[:, 6:8, :], hh[:, 6:8, 128:384], hh[:, 6:8, 64:320], ALU.subtract, ["hh"], XXW[6:8])
            else:
                P.tt("vector", xx[:, 0:4, :], hh[:, 0:4, 63:319], hh[:, 0:4, 64:320], ALU.subtract, ["hh"], XXW[0:4] + [("xxe", 0)])
                P.tt("gpsimd", xx[:, 4:8, :], hh[:, 4:8, 65:321], hh[:, 4:8, 64:320], ALU.subtract, ["hh"], XXW[4:8] + [("xxe", 1)])
            yield

            def mk_xj(j, buf, bn):
                for c in range(8):
                    P.stt(buf[:, c, :], xx[:, c, :], vec[:, 5 + j, c:c + 1], hc[:, c, :], ALU.mult, ALU.add,
                          [("xx", c), ("xxe", 0), ("xxe", 1), "hh", "vec"], [(bn, c)])

            mk_xj(1, xrot, "xrot")
            yield
            i = proj8(lambda c: lw1[:, c, :], xrot, "xrot", ["lw1"])
            P.act(lwt[:], pp[i], AF.Tanh, [f"pp{i}"], ["lwt"])
            yield
            mk_xj(4, xrot, "xrot")
            yield
            i = proj8(lambda c: la1[:, c, :], xrot, "xrot", ["la1"])
            P.cp("scalar", lat[:], pp[i], [f"pp{i}"], ["lat"])
            yield
            mk_xj(5, xrot, "xrot")
            yield
            i = proj8(lambda c: g1[:, c, :], xrot, "xrot", ["g1"])
            P.act(sg[:], pp[i], AF.Sigmoid, [f"pp{i}"], ["sg"])
            yield
            mk_xj(0, xr, "xr")
            yield
            mk_xj(2, xk, "xk")
            yield
            mk_xj(3, xv, "xv")
            if ti + 1 < len(RW_ORDER1):
                load_hh(ti + 1)
            yield

        def prep(ti, oc, w):
            isctx, idx = RW_ORDER1[ti]
            tg = 16 if isctx else idx
            cs = slice(oc * 128, (oc + 1) * 128)
            B_ = bufs[w]
            t, sqb, RK, VTbd, GTbd, Vf, Gf = B_["t"], B_["sqb"], B_["RK"], B_["VTbd"], B_["GTbd"], B_["Vf"], B_["Gf"]
            ops_st, vb_st, gam_st = B_["ops"], B_["vb"], B_["gam"]
            i = proj8(lambda c: wr[:, c, cs], xr, "xr", ["rwkv_wr"])
            P.cp("scalar", t["r"][:], pp[i], [f"pp{i}"], ["r"])
            i = proj8(lambda c: wk[:, c, cs], xk, "xk", ["rwkv_wk"])
            P.cp("scalar", t["k"][:], pp[i], [f"pp{i}"], ["k"])
            i = proj8(lambda c: wv[:, c, cs], xv, "xv", ["rwkv_wv"])
            vt4 = VTbd[:].rearrange("p u (h s) -> p u h s", h=2)
            for h2 in range(2):
                sl = slice(h2 * 64, (h2 + 1) * 64)
                P.cp("scalar", vt4[sl, :, h2, :], v3(pp[i][sl, :]), [f"pp{i}"], ["VTbd"])
            i = nxt("pp")
            P.mm(pp[i], g2[:, cs], sg[:], True, True, ["g2", "sg"], [f"pp{i}"])
            gt4 = GTbd[:].rearrange("p u (h s) -> p u h s", h=2)
            for h2 in range(2):
                sl = slice(h2 * 64, (h2 + 1) * 64)
                P.cp("scalar", gt4[sl, :, h2, :], v3(pp[i][sl, :]), [f"pp{i}"], ["GTbd"])
            yield
            j = nxt("pb")
            for u in range(4):
                P.tr(pb[j][:, u * 128:(u + 1) * 128], VTbd[:, u, :], identf[:], ["VTbd", "identf"], [f"pb{j}"])
            pv = u128(pb[j][:])
            for h2 in range(2):
                sl = slice(h2 * 64, (h2 + 1) * 64)
                P.cp("scalar", Vf[sl, :, :], pv[sl, :, h2 * 64:(h2 + 1) * 64], [f"pb{j}"], ["Vf"])
            P.cp("gpsimd", vb_st[:], Vf[:], ["Vf"], ["vb_st"])
            P.dma("sync", S["vst"][tg, oc].rearrange("p (u s) -> p u s", s=64), Vf[:], reads=["Vf"], writes=[("vst", tg, oc)], sem="Vf")
            j = nxt("pb")
            for u in range(4):
                P.tr(pb[j][:, u * 128:(u + 1) * 128], GTbd[:, u, :], identf[:], ["GTbd", "identf"], [f"pb{j}"])
            pv = u128(pb[j][:])
            for h2 in range(2):
                sl = slice(h2 * 64, (h2 + 1) * 64)
                P.cp("scalar", Gf[sl, :, :], pv[sl, :, h2 * 64:(h2 + 1) * 64], [f"pb{j}"], ["Gf"])
            P.dma("sync", S["gst"][tg, oc].rearrange("p (u s) -> p u s", s=64), Gf[:], reads=["Gf"], writes=[("gst", tg, oc)], sem="Gf")
            yield
            for d in range(2):
                dl = slice(d * 64, (d + 1) * 64)
                i = nxt("pp")
                P.mm(pp[i], w2s[dl, cs], lwt[dl, :], True, True, ["w2s", "lwt"], [f"pp{i}"])
                P.act(t[f"sw{d}"][:], pp[i], AF.Sigmoid, [f"pp{i}", "vec"], [f"sw{d}"], bias=vec[:, 11 + d, oc:oc + 1])
                i = nxt("pp")
                P.mm(pp[i], a2s[dl, cs], lat[dl, :], True, True, ["a2s", "lat"], [f"pp{i}"])
                P.act(t[f"ag{d}"][:], pp[i], AF.Sigmoid, [f"pp{i}", "vec"], [f"ag{d}"], bias=vec[:, 13 + d, oc:oc + 1])
            yield
            P.ts("vector", t["kq"][:], t["k"][:], vec[:, 15, oc:oc + 1], None, ALU.mult, None, ["k", "vec"], ["kq"])
            P.act(sqb[:], t["kq"][:], AF.Square, ["kq"], ["sqb"])
            i = nxt("pp")
            P.mm(pp[i], bones[:], sqb[:], True, True, ["bones", "sqb"], [f"pp{i}"])
            P.act(t["lnv"][:], pp[i], AF.Ln, [f"pp{i}"], ["lnv"], bias=1e-12)
            P.act(t["rs"][:], t["lnv"][:], AF.Exp, ["lnv"], ["rs"], scale=-0.5)
            P.tt("vector", t["kkn"][:], t["kq"][:], t["rs"][:], ALU.mult, ["kq", "rs"], ["kkn"])
            for d in range(2):
                sw, ag, kd, bb = t[f"sw{d}"], t[f"ag{d}"], t[f"kd{d}"], t[f"b{d}"]
                EE = "vector"
                P.ts(EE, t["fac"][:], ag[:], vec[:, 16, oc:oc + 1], vec[:, 17, oc:oc + 1], ALU.mult, ALU.add, [f"ag{d}", "vec"], ["fac"])
                P.tt(EE, kd[:], t["k"][:], t["fac"][:], ALU.mult, ["k", "fac"], [f"kd{d}"])
                P.tt(EE, bb[:], t["kkn"][:], ag[:], ALU.mult, ["kkn", f"ag{d}"], [f"b{d}"])
                P.op("vector", lambda e, sw=sw: e.tensor_tensor_scan(out=t["L"][:], data0=rmask[:], data1=sw[:], initial=0.0,
                                                                      op0=ALU.mult, op1=ALU.add), [f"sw{d}", "rmask"], ["L"])
                L3 = v3(t["L"][:])
                if d == 0:
                    P.tt(EE, t["Lx"][:], t["L"][:], sw[:], ALU.subtract, ["L", f"sw{d}"], ["Lx"])
                    Li, Lin = t["L"], "L"
                else:
                    P.tt(EE, v3(t["Lx"][:]), L3[:, :, 63:64].broadcast_to([128, 4, 64]), L3, ALU.subtract, ["L"], ["Lx"])
                    P.tt(EE, t["Lb"][:], t["Lx"][:], sw[:], ALU.add, ["Lx", f"sw{d}"], ["Lb"])
                    Li, Lin = t["Lb"], "Lb"
                P.act(t["E1"][:], Li[:], AF.Exp, [Lin], ["E1"], scale=-C0)
                P.act(t["E3"][:], Li[:], AF.Exp, [Lin], ["E3"], scale=C0)
                P.act(t["E2"][:], t["Lx"][:], AF.Exp, ["Lx"], ["E2"], scale=-C0)
                P.stt(ops_st[:, d, 0, :], t["kkn"][:], -1.0, t["E2"][:], ALU.mult, ALU.mult, ["kkn", "E2"], ["ops_st"])
                P.tt("vector", ops_st[:, d, 1, :], t["r"][:], t["E1"][:], ALU.mult, ["r", "E1"], ["ops_st"])
                P.tt(EE, ops_st[:, d, 2, :], kd[:], t["E3"][:], ALU.mult, [f"kd{d}", "E3"], ["ops_st"])
                P.tt(EE, ops_st[:, d, 3, :], bb[:], t["E3"][:], ALU.mult, [f"b{d}", "E3"], ["ops_st"])
                E13 = v3(t["E1"][:])
                gsrc = E13[:, :, 63] if d == 0 else E13[:, :, 0]
                P.cp("vector", gam_st[:, d, :], gsrc, ["E1"], ["gam_st"])
                if d == 1:
                    P.cp("gpsimd", gamb_t[:, oc, :], gam_st[:, 1, :], ["gam_st"], ["gamb_t"])
                yield
            P.tt("vector", t["ks"][:], t["kd0"][:], t["kd1"][:], ALU.add, ["kd0", "kd1"], ["ks"])
            for h2 in range(2):
                sl = slice(h2 * 64, (h2 + 1) * 64)
                P.stt(RK[sl, :, h2, :], v3(t["r"][sl, :]), vec[sl, 18, oc:oc + 1], v3(t["ks"][sl, :]), ALU.mult, ALU.mult, ["r", "ks", "vec"], ["RK"])
            i = nxt("pp")
            for u in range(4):
                P.mm(pp[i][:, u:u + 1], RK[:, u, :, :].rearrange("p h s -> p (h s)"), onesb[:, 0:1], True, True, ["RK", "onesb"], [f"pp{i}"])
            P.cp("scalar", bon_t[:, oc, :], pp[i][:, 0:4], [f"pp{i}"], ["bon_t"])
            P.dma("sync", S["ops"][tg, oc], ops_st[:].rearrange("p d x n -> p (d x n)"), reads=["ops_st"], writes=[("ops", tg, oc)], sem="ops_st")
            P.dma("sync", S["vb"][tg, oc], vb_st[:].rearrange("p u s -> p (u s)"), reads=["vb_st"], writes=[("vb", tg, oc)], sem="vb_st")
            P.dma("sync", S["gam"][tg, oc], gam_st[:].rearrange("p d u -> p (d u)"), reads=["gam_st"], writes=[("gam", tg, oc)], sem="gam_st")
            yield


        NT = len(RW_ORDER1)
        load_hh(0)
        for ti in range(NT):
            isctx, idx = RW_ORDER1[ti]
            tg = 16 if isctx else idx
            for _ in tprep(ti):
                pass
            jobs = [(oc % NSET, prep(ti, oc, oc % NSET)) for oc in range(8)]
            active = []
            since = 99
            while jobs or active:
                if jobs and len(active) < NSET and (since >= 3 or not active):
                    active.append(jobs.pop(0))
                    since = 0
                since += 1
                for item in list(active):
                    P.ns = item[0]
                    try:
                        next(item[1])
                    except StopIteration:
                        active.remove(item)
                    P.ns = None
            P.dma("sync", S["gamb"][tg], gamb_t[:].rearrange("p a b -> p (a b)"), reads=["gamb_t"], writes=[("gamb", tg)], sem="gamb_t")
            P.dma("sync", S["bon"][tg], bon_t[:].rearrange("p a b -> p (a b)"), reads=["bon_t"], writes=[("bon", tg)], sem="bon_t")
        P.ns_set = frozenset()


def stage_rwkv1b(P, io, G, S):
    vec, masks, identb, identf, bones, onesb, rmask = (G[k] for k in ("vec", "masks", "identb", "identf", "bones", "onesb", "rmask"))
    with P.phase("rwkv1b"):
        YPs = P.sb([128, 4, 64], F32)
        SAs = P.sb([128, 4, 64], F32)
        Sf = P.sb([128, 8, 64], BF16)
        ARq = [[P.sb([128, 4, 2, 128], BF16, f"AR{q}{d}") for d in range(2)] for q in range(3)]
        KTq = [[P.sb([128, 4, 128], BF16, f"KT{q}{d}") for d in range(2)] for q in range(3)]
        BTq = [[P.sb([128, 4, 128], BF16, f"BT{q}{d}") for d in range(2)] for q in range(3)]
        stg = [P.sb([128, 2, 4, 256], BF16, f"stg{q}") for q in range(3)]
        Vbq = [P.sb([128, 4, 64], BF16, f"Vb{q}") for q in range(4)]
        gamq = [P.sb([128, 2, 4], F32, f"gam{q}") for q in range(4)]
        inv2 = []
        for q in range(2):
            row = []
            for d in range(2):
                st = {}
                for nm, shp in (("Atok", [128, 4, 128]), ("Btok", [128, 4, 128]), ("MQ", [128, 4, 256]), ("MWa", [128, 4, 2, 128]),
                                ("MWb", [128, 4, 2, 128]), ("MTa", [128, 4, 128]), ("MTb", [128, 4, 128])):
                    st[nm] = P.sb(shp, BF16, f"i{q}{d}_{nm}")
                row.append(st)
            inv2.append(row)
        fin = []
        for q in range(2):
            row = []
            for d in range(2):
                st = {}
                for nm, shp in (("Ktok", [128, 4, 128]), ("NP", [128, 4, 256]), ("XW", [128, 4, 256]), ("NVb", [128, 4, 64]),
                                ("GY", [128, 4, 128]), ("GS", [128, 4, 128])):
                    st[nm] = P.sb(shp, BF16, f"f{q}{d}_{nm}")
                row.append(st)
            fin.append(row)
        pf = P.ps([128, 512], F32)
        pb = [P.ps([128, 512], F32) for _ in range(7)]
        cnt = {"pb": 0}
        nmod = {"pb": 7}

        def nxt(kind):
            i = cnt[kind] % nmod[kind]
            cnt[kind] += 1
            return i

        for q in range(3):
            for d in range(2):
                P.memset("gpsimd", ARq[q][d][:], 0.0, [f"AR{q}{d}"])
                P.memset("gpsimd", KTq[q][d][:], 0.0, [f"KT{q}{d}"])
                P.memset("gpsimd", BTq[q][d][:], 0.0, [f"BT{q}{d}"])
        P.memset("gpsimd", Sf[:], 0.0, [("Sf", p) for p in range(8)])

        def v3(ap):
            return ap.rearrange("p (u s) -> p u s", s=64)

        def u128(ap):
            return ap.rearrange("p (u x) -> p u x", x=128)

        def loadjob(ti, oc, a, z):
            isctx, idx = RW_ORDER1[ti]
            tg = 16 if isctx else idx
            sg_ = stg[a]
            P.dma("sync", sg_[:].rearrange("p d x n -> p (d x n)"), S["ops"][tg, oc], writes=[f"stg{a}"], sem=f"stg{a}")
            P.dma("sync", Vbq[z][:].rearrange("p u s -> p (u s)"), S["vb"][tg, oc], writes=[f"Vb{z}"], sem=f"Vb{z}")
            P.dma("sync", gamq[z][:].rearrange("p d u -> p (d u)"), S["gam"][tg, oc], writes=[f"gam{z}"], sem=f"gam{z}")
            yield
            for d in range(2):
                ar5 = ARq[a][d][:].rearrange("p u a (h s) -> p u a h s", h=2)
                kt4 = KTq[a][d][:].rearrange("p u (h s) -> p u h s", h=2)
                bt4 = BTq[a][d][:].rearrange("p u (h s) -> p u h s", h=2)
                for h2 in range(2):
                    sl = slice(h2 * 64, (h2 + 1) * 64)
                    P.cp("gpsimd", ar5[sl, :, 0, h2, :], v3(sg_[sl, d, 0, :]), [f"stg{a}"], [f"AR{a}{d}"])
                    P.cp("gpsimd", ar5[sl, :, 1, h2, :], v3(sg_[sl, d, 1, :]), [f"stg{a}"], [f"AR{a}{d}"])
                    P.cp("gpsimd", kt4[sl, :, h2, :], v3(sg_[sl, d, 2, :]), [f"stg{a}"], [f"KT{a}{d}"])
                    P.cp("gpsimd", bt4[sl, :, h2, :], v3(sg_[sl, d, 3, :]), [f"stg{a}"], [f"BT{a}{d}"])
                    yield

        def chain(ti, oc, q, d, z, a):
            AR, KT, BT, Vb = ARq[a][d], KTq[a][d], BTq[a][d], Vbq[z]
            ARn, KTn, BTn, Vbn = f"AR{a}{d}", f"KT{a}{d}", f"BT{a}{d}", f"Vb{z}"
            iv, fn = inv2[q][d], fin[q][d]
            IR = lambda nm: f"i{q}{d}_{nm}"
            FR = lambda nm: f"f{q}{d}_{nm}"
            mS, mC = (0, 2) if d == 0 else (2, 0)
            mSI = masks[:, mS:mS + 2, :].rearrange("p a b -> p (a b)").unsqueeze(1).broadcast_to([128, 4, 256])
            mCb = masks[:, mC, :].unsqueeze(1).broadcast_to([128, 4, 128])
            idb = identb[:].unsqueeze(1).broadcast_to([128, 4, 128])
            for src, srcn, dst, dstn in ((AR[:, :, 0, :], ARn, iv["Atok"], IR("Atok")), (BT[:], BTn, iv["Btok"], IR("Btok")),
                                         (KT[:], KTn, fn["Ktok"], FR("Ktok"))):
                j = nxt("pb")
                pbt = pb[j][:].bitcast(BF16)
                for u in range(4):
                    P.tr(pbt[:, u * 128:(u + 1) * 128], src[:, u, :], identb[:], [srcn, "identb"], [f"pb{j}"])
                P.cp("scalar", dst[:].rearrange("p u x -> p (u x)"), pbt[:, 0:512], [f"pb{j}"], [dstn])
            mSb = masks[:, mS, :].unsqueeze(1).broadcast_to([128, 4, 128])
            mIb = masks[:, mS + 1, :].unsqueeze(1).broadcast_to([128, 4, 128])

            def two_bank(mm_fn):
                j0, j1 = nxt("pb"), nxt("pb")
                for u in range(4):
                    mm_fn(u, pb[j0][:, u * 128:(u + 1) * 128], f"pb{j0}", pb[j1][:, u * 128:(u + 1) * 128], f"pb{j1}")
                return j0, j1

            for lhs, lhsn, dst, dstn in ((BT, BTn, iv["MQ"], IR("MQ")), (KT, KTn, fn["NP"], FR("NP"))):
                def mm_ab(u, o0, n0, o1, n1, lhs=lhs, lhsn=lhsn):
                    P.mm(o0, lhs[:, u, :], AR[:, u, 0, :], True, True, [lhsn, ARn], [n0])
                    P.mm(o1, lhs[:, u, :], AR[:, u, 1, :], True, True, [lhsn, ARn], [n1])
                j0, j1 = two_bank(mm_ab)
                P.tt("vector", dst[:, :, 0:128], u128(pb[j0][:]), mSb, ALU.mult, [f"pb{j0}", "masks"], [dstn])
                P.tt("vector", dst[:, :, 128:256], u128(pb[j1][:]), mIb, ALU.mult, [f"pb{j1}", "masks"], [dstn])
            j = nxt("pb")
            for u in range(4):
                P.mm(pb[j][:, u * 128:(u + 1) * 128], AR[:, u, 0, :], BT[:, u, :], True, True, [ARn, BTn], [f"pb{j}"])
            cur, curn, nx, nxn = iv["MWa"], IR("MWa"), iv["MWb"], IR("MWb")
            P.tt("vector", cur[:, :, 0, :], u128(pb[j][:]), mCb, ALU.mult, [f"pb{j}", "masks"], [curn])
            yield
            j = nxt("pb")
            for u in range(4):
                P.mm(pb[j][:, u * 128:(u + 1) * 128], iv["MQ"][:, u, 0:128], cur[:, u, 0, :], True, True, [IR("MQ"), curn], [f"pb{j}"])
            P.cp("scalar", nx[:, :, 0, :], u128(pb[j][:]), [f"pb{j}"], [nxn])
            P.tt("gpsimd", nx[:, :, 1, :], cur[:, :, 0, :], idb, ALU.add, [curn, "identb"], [nxn])
            j = nxt("pb")
            for u in range(4):
                P.mm(pb[j][:, u * 128:(u + 1) * 128], cur[:, u, 0, :], iv["MQ"][:, u, 0:128], True, True, [IR("MQ"), curn], [f"pb{j}"])
            curT, curTn, nxT, nxTn = iv["MTa"], IR("MTa"), iv["MTb"], IR("MTb")
            P.cp("scalar", curT[:], u128(pb[j][:]), [f"pb{j}"], [curTn])
            cur, curn, nx, nxn = nx, nxn, cur, curn
            yield
            for lev in range(1, 5):
                def mm_lev(u, o0, n0, o1, n1, cur=cur, curn=curn, curT=curT, curTn=curTn):
                    P.mm(o0, curT[:, u, :], cur[:, u, 0, :], True, True, [curTn, curn], [n0])
                    P.mm(o1, curT[:, u, :], cur[:, u, 1, :], True, True, [curTn, curn], [n1])
                j0, j1 = two_bank(mm_lev)
                P.cp("scalar", nx[:, :, 0, :], u128(pb[j0][:]), [f"pb{j0}"], [nxn])
                P.tt("vector", nx[:, :, 1, :], u128(pb[j1][:]), cur[:, :, 1, :], ALU.add, [f"pb{j1}", curn], [nxn])
                j = nxt("pb")
                for u in range(4):
                    P.mm(pb[j][:, u * 128:(u + 1) * 128], cur[:, u, 0, :], curT[:, u, :], True, True, [curn, curTn], [f"pb{j}"])
                P.cp("scalar", nxT[:], u128(pb[j][:]), [f"pb{j}"], [nxTn])
                cur, curn, nx, nxn = nx, nxn, cur, curn
                curT, curTn, nxT, nxTn = nxT, nxTn, curT, curTn
                yield
            j = nxt("pb")
            for u in range(4):
                P.mm(pb[j][:, u * 128:(u + 1) * 128], curT[:, u, :], cur[:, u, 1, :], True, True, [curTn, curn], [f"pb{j}"])
            P.tt("vector", nx[:, :, 1, :], u128(pb[j][:]), cur[:, :, 1, :], ALU.add, [f"pb{j}", curn], [nxn])
            W6, W6n = nx, nxn
            j = nxt("pb")
            for u in range(4):
                P.mm(pb[j][:, u * 64:(u + 1) * 64], fn["NP"][:, u, 0:128], Vb[:, u, :], True, True, [FR("NP"), Vbn], [f"pb{j}"])
            P.cp("scalar", fn["NVb"][:].rearrange("p u x -> p (u x)"), pb[j][:, 0:256], [f"pb{j}"], [FR("NVb")])
            yield

            def mm_d(u, o0, n0, o1, n1):
                P.mm(o0, W6[:, u, 1, :], iv["MQ"][:, u, 128:256], True, True, [W6n, IR("MQ")], [n0])
                P.mm(o1, W6[:, u, 1, :], iv["Btok"][:, u, :], True, True, [W6n, IR("Btok")], [n1])
            j0, j1 = two_bank(mm_d)
            P.cp("scalar", fn["XW"][:, :, 0:128], u128(pb[j0][:]), [f"pb{j0}"], [FR("XW")])
            P.cp("vector", fn["XW"][:, :, 128:256], u128(pb[j1][:]), [f"pb{j1}"], [FR("XW")])
            yield

            def mm_f(u, o0, n0, o1, n1):
                P.mm(o0, iv["Atok"][:, u, :], fn["XW"][:, u, 0:128], True, True, [IR("Atok"), FR("XW")], [n0])
                P.mm(o1, iv["Atok"][:, u, :], fn["XW"][:, u, 128:256], True, True, [IR("Atok"), FR("XW")], [n1])
            j0, j1 = two_bank(mm_f)
            P.tt("vector", fn["GY"][:], u128(pb[j0][:]), AR[:, :, 1, :], ALU.add, [f"pb{j0}", ARn], [FR("GY")])
            P.tt("vector", fn["GS"][:], u128(pb[j1][:]), idb, ALU.add, [f"pb{j1}", "identb"], [FR("GS")])
            yield

        def finish(ti, oc, q, z):
            isctx, idx = RW_ORDER1[ti]
            tg = 16 if isctx else idx
            sf, sb_ = fin[q]
            F0 = lambda nm: f"f{q}0_{nm}"
            F1 = lambda nm: f"f{q}1_{nm}"
            Vb, Vbn, gamz = Vbq[z], f"Vb{z}", gamq[z]
            SFR = ("Sf", oc)
            for u in range(4):
                yo = pf[:, u * 64:(u + 1) * 64]
                P.mm(yo, sf["NP"][:, u, 128:256], Vb[:, u, :], True, False, [F0("NP"), Vbn], ["pf"])
                P.mm(yo, sf["XW"][:, u, 0:128], sf["NVb"][:, u, :], False, False, [F0("XW"), F0("NVb")], ["pf"])
                P.mm(yo, sb_["NP"][:, u, 128:256], Vb[:, u, :], False, False, [F1("NP"), Vbn], ["pf"])
                P.mm(yo, sb_["XW"][:, u, 0:128], sb_["NVb"][:, u, :], False, False, [F1("XW"), F1("NVb")], ["pf"])
                P.mm(yo, sf["GY"][:, u, :], Sf[:, oc, :], False, True, [F0("GY"), SFR], ["pf"])
                so = pf[:, 256:320]
                P.mm(so, sf["Ktok"][:, u, :], Vb[:, u, :], True, False, [F0("Ktok"), Vbn], ["pf"])
                P.mm(so, sf["XW"][:, u, 128:256], sf["NVb"][:, u, :], False, False, [F0("XW"), F0("NVb")], ["pf"])
                P.mm(so, sf["GS"][:, u, :], Sf[:, oc, :], False, True, [F0("GS"), SFR], ["pf"])
                P.ts("vector", Sf[:, oc, :], so, gamz[:, 0, u:u + 1], None, ALU.mult, None, ["pf", f"gam{z}"], [SFR])
                yield
            P.cp("vector", YPs[:].rearrange("p u x -> p (u x)"), pf[:, 0:256], ["pf"], ["YPs"])
            P.dma("sync", S["yp"][tg, oc], YPs[:].rearrange("p u x -> p (u x)"), reads=["YPs"], writes=[("yp", tg, oc)], sem="YPs")
            j = nxt("pb")
            for u in range(4):
                so = pb[j][:, u * 64:(u + 1) * 64]
                P.mm(so, sb_["Ktok"][:, u, :], Vb[:, u, :], True, False, [F1("Ktok"), Vbn], [f"pb{j}"])
                P.mm(so, sb_["XW"][:, u, 128:256], sb_["NVb"][:, u, :], False, True, [F1("XW"), F1("NVb")], [f"pb{j}"])
            P.cp("scalar", SAs[:].rearrange("p u x -> p (u x)"), pb[j][:, 0:256], [f"pb{j}"], ["SAs"])
            P.dma("sync", S["sadd"][tg, oc], SAs[:].rearrange("p u x -> p (u x)"), reads=["SAs"], writes=[("sadd", tg, oc)], sem="SAs")
            P.dma("sync", S["gyb"][tg, oc], sb_["GY"][:].rearrange("p u x -> p (u x)"), reads=[F1("GY")], writes=[("gyb", tg, oc)], sem=F1("GY"))
            P.dma("sync", S["gsb"][tg, oc], sb_["GS"][:].rearrange("p u x -> p (u x)"), reads=[F1("GS")], writes=[("gsb", tg, oc)], sem=F1("GS"))
            yield


        NT = len(RW_ORDER1)
        NJ = NT * 8
        donef = set()

        def stream_L():
            for k in range(NJ):
                ti, oc = divmod(k, 8)
                yield ("load", k, lambda k=k: ((k < 3 or (("c0", k - 3) in donef and ("c1", k - 3) in donef)) and (k < 4 or ("fin", k - 4) in donef)),
                       lambda ti=ti, oc=oc, k=k: loadjob(ti, oc, k % 3, k % 4))

        def stream_C(d, par):
            for k in range(par, NJ, 2):
                ti, oc = divmod(k, 8)
                yield (f"c{d}", k, lambda k=k: (("load", k) in donef and (k < 2 or ("fin", k - 2) in donef)),
                       lambda ti=ti, oc=oc, k=k: chain(ti, oc, k % 2, d, k % 4, k % 3))

        def stream_F():
            for k in range(NJ):
                ti, oc = divmod(k, 8)
                yield ("fin", k, lambda k=k: (("c0", k) in donef and ("c1", k) in donef),
                       lambda ti=ti, oc=oc, k=k: finish(ti, oc, k % 2, k % 4))

        streams = [stream_L(), stream_C(0, 0), stream_C(1, 0), stream_C(0, 1), stream_C(1, 1), stream_F()]
        NS_ = len(streams)
        cur = [None] * NS_
        pend = [None] * NS_
        alive = [True] * NS_
        while any(alive):
            progressed = False
            for si in range(NS_):
                if not alive[si]:
                    continue
                if cur[si] is None:
                    if pend[si] is None:
                        try:
                            pend[si] = next(streams[si])
                        except StopIteration:
                            alive[si] = False
                            continue
                    kind, k, ready, mk = pend[si]
                    if not ready():
                        continue
                    cur[si] = (kind, k, mk())
                    pend[si] = None
                kind, k, gen = cur[si]
                try:
                    next(gen)
                    progressed = True
                except StopIteration:
                    donef.add((kind, k))
                    cur[si] = None
                    progressed = True
            assert progressed or not any(alive), "scheduler stuck"


def stage_rwkv2(P, io, G, S, src, xa):
    vec, identb = G["vec"], G["identb"]
    GN_EPS = 64e-5
    with P.phase("rwkv2"):
        wo = P.sb([64, 16, 1024], BF16)
        P.dma("gpsimd", wo[:], io["rwkv_wo"].rearrange("(h v) f -> v h f", v=64), writes=["wo"], sem="wo")
        lnw = P.sb([128, 8, 64], F32)
        lnb = P.sb([128, 8, 64], F32)
        P.dma("sync", lnw[:], io["lnw_st"], writes=["lnw"], sem="lnw")
        P.dma("sync", lnb[:], io["lnb_st"], writes=["lnb"], sem="lnb")
        big = {}
        for nm in ("yp", "sadd", "vst", "gst"):
            big[nm] = [P.sb([128, 8, 256], F32, f"l_{nm}{b}") for b in range(2)]
        for nm in ("gyb", "gsb"):
            big[nm] = [P.sb([128, 8, 512], BF16, f"l_{nm}{b}") for b in range(2)]
        gamb = [P.sb([128, 8, 4], F32) for _ in range(2)]
        bon = [P.sb([128, 8, 4], F32) for _ in range(2)]
        xt = [P.sb([128, 8, 256], F32) for _ in range(2)]
        Sb = P.sb([128, 8, 64], BF16)
        ysb2 = [P.sb([128, 8, 64], F32) for _ in range(2)]
        ysq2 = [P.sb([128, 8, 64], F32) for _ in range(2)]
        tmpS = P.sb([128, 8, 64], F32)
        yn2 = [P.sb([128, 8, 64], F32) for _ in range(2)]
        bv2 = [P.sb([128, 8, 64], F32) for _ in range(2)]
        ob2 = [P.sb([128, 8, 64], BF16) for _ in range(2)]
        st2 = [{nm: P.sb([128, 8], F32, f"g{k_}_" + nm) for nm in ("s1", "s2", "mean", "msq", "var", "lnv", "rstd")} for k_ in range(2)]
        OT = P.sb([64, 16, 256], BF16)
        py = [P.ps([128, 512], F32) for _ in range(2)]
        pS = P.ps([128, 512], F32)
        ptr = P.ps([128, 1024], F32)
        pw = [P.ps([128, 512], F32) for _ in range(2)]
        P.memset("gpsimd", Sb[:], 0.0, ["Sb"])

        def load(k):
            isctx, idx = RW_ORDER2[k]
            tg = 16 if isctx else idx
            b = k % 2
            for nm in ("yp", "sadd", "vst", "gst", "gyb", "gsb"):
                P.dma("sync", big[nm][b][:], S[nm][tg].rearrange("o p x -> p o x"), writes=[f"{nm}{b}"], sem=f"{nm}{b}")
            P.dma("sync", gamb[b][:].rearrange("p a b -> p (a b)"), S["gamb"][tg], writes=[f"gamb{b}"], sem=f"gamb{b}")
            P.dma("sync", bon[b][:].rearrange("p a b -> p (a b)"), S["bon"][tg], writes=[f"bon{b}"], sem=f"bon{b}")
            c0 = T if isctx else idx * 256
            P.dma("sync", xt[b][:], fm(src[:, c0:c0 + 256]), writes=[f"xt{b}"], sem=f"xt{b}")

        load(0)
        for k, (isctx, idx) in enumerate(RW_ORDER2):
            b = k % 2
            if k + 1 < len(RW_ORDER2):
                load(k + 1)
            c0 = T if isctx else idx * 256
            _, _, gates = mod_scalars(G, 0, 0, isctx)
            bc = lambda ap: ap.unsqueeze(2).broadcast_to([128, 8, 64])
            def chain_part(u):
                us = slice(u * 64, (u + 1) * 64)
                q_ = u % 2
                for oc in range(8):
                    P.mm(py[q_][:, oc * 64:(oc + 1) * 64], big["gyb"][b][:, oc, u * 128:(u + 1) * 128], Sb[:, oc, :], True, True, [f"gyb{b}", "Sb"], [f"py{q_}"])
                for oc in range(8):
                    P.mm(pS[:, oc * 64:(oc + 1) * 64], big["gsb"][b][:, oc, u * 128:(u + 1) * 128], Sb[:, oc, :], True, True, [f"gsb{b}", "Sb"], ["pS"])
                pS3 = pS[:].rearrange("p (o v) -> p o v", v=64)
                P.tt("vector", tmpS[:], pS3, big["sadd"][b][:, :, us], ALU.add, ["pS", f"sadd{b}"], ["tmpS"])
                P.tt("vector", Sb[:], tmpS[:], bc(gamb[b][:, :, u]), ALU.mult, ["tmpS", f"gamb{b}"], ["Sb"])

            def read_part(u):
                us = slice(u * 64, (u + 1) * 64)
                q_ = u % 2
                ysb, ysq, yn, bv, ob, st = ysb2[q_], ysq2[q_], yn2[q_], bv2[q_], ob2[q_], st2[q_]
                N = lambda nm: f"{nm}{q_}"
                py3 = py[q_][:].rearrange("p (o v) -> p o v", v=64)
                P.tt("vector", ysb[:], py3, big["yp"][b][:, :, us], ALU.add, [f"py{q_}", f"yp{b}"], [N("ysb")])
                P.tt("gpsimd", bv[:], big["vst"][b][:, :, us], bc(bon[b][:, :, u]), ALU.mult, [f"vst{b}", f"bon{b}"], [N("bv")])
                yield
                P.op("vector", lambda e: e.tensor_reduce(out=st["s1"][:], in_=ysb[:], axis=AX.X, op=ALU.add), [N("ysb")], [N("s1")])
                P.tt("gpsimd", ysq[:], ysb[:], ysb[:], ALU.mult, [N("ysb")], [N("ysq")])
                yield
                P.op("vector", lambda e: e.tensor_reduce(out=st["s2"][:], in_=ysq[:], axis=AX.X, op=ALU.add), [N("ysq")], [N("s2")])
                P.ts("vector", st["mean"][:], st["s1"][:], 1.0 / 64, None, ALU.mult, None, [N("s1")], [N("mean")])
                P.tt("vector", st["msq"][:], st["mean"][:], st["mean"][:], ALU.mult, [N("mean")], [N("msq")])
                P.stt(st["var"][:], st["s2"][:], 1.0 / 64, st["msq"][:], ALU.mult, ALU.subtract, [N("s2"), N("msq")], [N("var")])
                yield
                P.act(st["lnv"][:], st["var"][:], AF.Ln, [N("var")], [N("lnv")], bias=GN_EPS)
                P.act(st["rstd"][:], st["lnv"][:], AF.Exp, [N("lnv")], [N("rstd")], scale=-0.5)
                P.tt("gpsimd", yn[:], ysb[:], bc(st["mean"][:]), ALU.subtract, [N("ysb"), N("mean")], [N("yn")])
                yield
                P.tt("vector", yn[:], yn[:], bc(st["rstd"][:]), ALU.mult, [N("yn"), N("rstd")], [N("yn")])
                yield
                P.tt("gpsimd", yn[:], yn[:], lnw[:], ALU.mult, [N("yn"), "lnw"], [N("yn")])
                yield
                P.tt("vector", yn[:], yn[:], lnb[:], ALU.add, [N("yn"), "lnb"], [N("yn")])
                yield
                P.tt("gpsimd", yn[:], yn[:], bv[:], ALU.add, [N("yn"), N("bv")], [N("yn")])
                yield
                P.tt("vector", ob[:], yn[:], big["gst"][b][:, :, us], ALU.mult, [N("yn"), f"gst{b}"], [N("ob")])
                yield
                ptb = ptr[:].bitcast(BF16)
                for oc in range(8):
                    P.tr(ptb[0:64, oc * 128:(oc + 1) * 128], ob[:, oc, :], identb[:], [N("ob"), "identb"], ["ptr"])
                P.cp("scalar", OT[:, :, us], ptb[0:64, 0:1024].rearrange("p (h t) -> p h t", t=64), ["ptr"], ["OT"])
                yield

            def chain_all():
                for u in range(3, -1, -1):
                    chain_part(u)
                    yield

            jobs = [read_part(u) for u in range(3, -1, -1)]
            cgen = chain_all()
            next(cgen)
            active = []
            started = 0
            while jobs or active:
                while jobs and len(active) < 2:
                    if started >= 1:
                        try:
                            next(cgen)
                        except StopIteration:
                            pass
                    active.append(jobs.pop(0))
                    started += 1
                for gen in list(active):
                    try:
                        next(gen)
                    except StopIteration:
                        active.remove(gen)
            for oc in range(8):
                j = oc % 2
                for h in range(16):
                    P.mm(pw[j][:, 0:256], wo[:, h, oc * 128:(oc + 1) * 128], OT[:, h, :], h == 0, h == 15, ["wo", "OT"], [f"pw{j}"])
                P.stt(xt[b][:, oc, :], pw[j][:, 0:256], gates[oc], xt[b][:, oc, :], ALU.mult, ALU.add, [f"pw{j}", f"xt{b}", "modv"], [f"xt{b}"])
            P.dma("sync", fm(xa[:, c0:c0 + 256]), xt[b][:], reads=[f"xt{b}"], writes=[("xa", k)], sem=f"xt{b}")


def stage_qkv(P, io, G, hb, qtd, Kz, VA):
    vec, bones, perm = G["vec"], G["bones"], G["perm"]
    with P.phase("qkv"):
        wq = P.sb([128, 8, 1024], BF16)
        wkd = P.sb([128, 8, 512], BF16)
        wv = P.sb([128, 8, 256], BF16)
        P.dma("gpsimd", wq[:], fm(io["attn_wq"]), writes=["wq"], sem="wq")
        P.dma("gpsimd", wkd[:], fm(io["attn_wkd"]), writes=["wkd"], sem="wkd")
        P.dma("gpsimd", wv[:], fm(io["attn_wv"]), writes=["wv"], sem="wv")
        ht = [P.sb([128, 8, 512], BF16) for _ in range(2)]
        cs = [P.sb([128, 512], F32) for _ in range(2)]
        sn = [P.sb([128, 512], F32) for _ in range(2)]
        NB = 2
        qf = [P.sb([128, 512], F32) for _ in range(NB)]
        sqb = [P.sb([128, 512], BF16) for _ in range(NB)]
        lnv = [P.sb([128, 512], F32) for _ in range(NB)]
        rstd = [P.sb([128, 512], F32) for _ in range(NB)]
        qh = [P.sb([128, 512], F32) for _ in range(NB)]
        qhb = [P.sb([128, 512], BF16) for _ in range(NB)]
        t1 = [P.sb([128, 512], F32) for _ in range(NB)]
        t2 = [P.sb([128, 512], F32) for _ in range(NB)]
        qst = [P.sb([128, 8, 512], BF16) for _ in range(2)]
        pp = [P.ps([128, 512], F32) for _ in range(6)]
        cnt = [0, 0]

        def nxt():
            cnt[0] += 1
            return cnt[0] % 6

        P.memset("gpsimd", VA[:], 0.0, ["VA0"])
        P.memset("gpsimd", VA[:].rearrange("p k (j x) -> p k j x", x=65)[:, :, 0:5, 64:65], 1.0, ["VA0"])
        P.memset("gpsimd", Kz[0][64:128, :, :], 0.0, ["Kz0z"])
        P.memset("gpsimd", Kz[1][0:64, :, :], 0.0, ["Kz1z"])
        tiles = ALL_TILES

        def load(i):
            c0, tw, isctx = tiles[i]
            b = i % 2
            P.dma("sync", ht[b][:, :, :tw], fm(hb[:, c0:c0 + tw]), writes=[f"ht{b}"], sem=f"ht{b}")
            if not isctx:
                P.dma("sync", cs[b][:, :tw], io["cosT"][:, c0:c0 + tw], writes=[f"cs{b}"], sem=f"cs{b}")
                P.dma("sync", sn[b][:, :tw], io["sinT"][:, c0:c0 + tw], writes=[f"sn{b}"], sem=f"sn{b}")

        def normrope(wcols, nscal, dsts, b, tw, isctx, wname, dres="dstqk"):
            cnt[1] += 1
            n = cnt[1] % NB
            i = nxt()
            for c in range(8):
                P.mm(pp[i][:, :tw], wcols(c), ht[b][:, c, :tw], c == 0, c == 7, [wname, f"ht{b}"], [f"pp{i}"])
            P.cp("scalar", qf[n][:, :tw], pp[i][:, :tw], [f"pp{i}"], [f"qf{n}"])
            P.act(sqb[n][:, :tw], qf[n][:, :tw], AF.Square, [f"qf{n}"], [f"sqb{n}"])
            yield
            i = nxt()
            P.mm(pp[i][:, :tw], bones[:], sqb[n][:, :tw], True, True, ["bones", f"sqb{n}"], [f"pp{i}"])
            P.act(lnv[n][:, :tw], pp[i][:, :tw], AF.Ln, [f"pp{i}"], [f"lnv{n}"], bias=1e-6, scale=1.0 / 64)
            P.act(rstd[n][:, :tw], lnv[n][:, :tw], AF.Exp, [f"lnv{n}"], [f"rstd{n}"], scale=-0.5)
            yield
            P.stt(qh[n][:, :tw], qf[n][:, :tw], nscal, rstd[n][:, :tw], ALU.mult, ALU.mult, [f"qf{n}", f"rstd{n}", "vec"], [f"qh{n}"])
            if isctx:
                for dst, sl in dsts:
                    P.cp("gpsimd", dst, qh[n][sl, :tw], [f"qh{n}"], [dres])
                return
            P.cp("gpsimd", qhb[n][:, :tw], qh[n][:, :tw], [f"qh{n}"], [f"qhb{n}"])
            yield
            i = nxt()
            P.mm(pp[i][:, :tw], perm[:], qhb[n][:, :tw], True, True, ["perm", f"qhb{n}"], [f"pp{i}"])
            P.tt("gpsimd", t1[n][:, :tw], qh[n][:, :tw], cs[b][:, :tw], ALU.mult, [f"qh{n}", f"cs{b}"], [f"t1{n}"])
            P.tt("vector", t2[n][:, :tw], pp[i][:, :tw], sn[b][:, :tw], ALU.mult, [f"pp{i}", f"sn{b}"], [f"t2{n}"])
            yield
            for dst, sl in dsts:
                P.tt("gpsimd", dst, t1[n][sl, :tw], t2[n][sl, :tw], ALU.add, [f"t1{n}", f"t2{n}"], [dres])

        ALLP = slice(0, 128)
        load(0)
        for i, (c0, tw, isctx) in enumerate(tiles):
            b = i % 2
            if i + 1 < len(tiles):
                load(i + 1)
            jobs = []
            if not isctx:
                for oc in range(8):
                    jobs.append(normrope(lambda c, oc=oc: wq[:, c, oc * 128:(oc + 1) * 128], vec[:, 19, oc:oc + 1], [(qst[b][:, oc, :tw], ALLP)], b, tw, False, "wq",
                                         dres=(f"qst{b}", oc)))
            for g in range(4):
                jobs.append(normrope(lambda c, g=g: wkd[:, c, g * 128:(g + 1) * 128], vec[:, 20, 0:1],
                                     [(Kz[0][0:64, g, c0:c0 + tw], slice(0, 64)), (Kz[1][64:128, g, c0:c0 + tw], slice(64, 128))], b, tw, isctx, "wkd"))

            def vjob():
                for sub in range(tw // 128):
                    kt = c0 // 128 + sub
                    j = nxt()
                    for c in range(8):
                        P.mm(pp[j][:, 0:256], ht[b][:, c, sub * 128:(sub + 1) * 128], wv[:, c, :], c == 0, c == 7, ["wv", f"ht{b}"], [f"pp{j}"])
                    P.cp("scalar", VA[:, kt, 65:325].rearrange("p (g x) -> p g x", x=65)[:, :, 0:64],
                         pp[j][:, 0:256].rearrange("p (g d) -> p g d", d=64), [f"pp{j}", "VA0"], [("VA", kt)])
                    yield

            jobs.append(vjob())
            active = []
            while jobs or active:
                while jobs and len(active) < 2:
                    active.append(jobs.pop(0))
                for gen in list(active):
                    try:
                        next(gen)
                    except StopIteration:
                        active.remove(gen)
            if not isctx:
                P.dma("sync", fm(qtd[:, c0:c0 + tw]), qst[b][:, :, :tw], reads=[(f"qst{b}", oc) for oc in range(8)], writes=[("qtd", i)], sem=f"qst{b}")


def stage_attn(P, io, G, qtd, Kz, VA, xa):
    with P.phase("attn"):
        wo = P.sb([128, 8, 1024], BF16)
        P.dma("gpsimd", wo[:], fm(io["attn_wo"]), writes=["wo"], sem="wo")
        sel = P.sb([128, 2, 128], F32)
        P.dma("sync", sel[:], io["c_sel"], writes=["sel"], sem="sel")
        PT = [P.sb([128, 1024], BF16) for _ in range(3)]
        osb = [P.sb([128, 512], F32) for _ in range(2)]
        rb = [P.sb([128, 512], F32) for _ in range(2)]
        xt = P.sb([128, 8, 512], F32)
        QB = [P.sb([128, 8, 512], BF16) for _ in range(2)]
        psS = [P.ps([128, 1024], F32) for _ in range(2)]
        psO = [P.ps([128, 512], F32) for _ in range(2)]
        psB = P.ps([128, 512], F32)
        pX = [P.ps([128, 512], F32) for _ in range(1)]
        _, _, gates = mod_scalars(G, 1, 0, False)
        for k in range(2):
            P.memset("gpsimd", osb[k][:], 0.0, [f"osb{k}"])
        def loadq(qb):
            P.dma("sync", QB[qb % 2][:], fm(qtd[:, qb * 512:(qb + 1) * 512]), writes=[("QT", h, qb) for h in range(16)], sem=f"QB{qb % 2}")

        loadq(0)
        for qb in range(8):
            qsl = slice(qb * 512, (qb + 1) * 512)
            QT = QB[qb % 2]
            if qb + 1 < 8:
                loadq(qb + 1)
            P.dma("sync", xt[:], fm(xa[:, qsl]), writes=["xt"], sem="xt")
            steps = [(h, kp) for h in range(16) for kp in range(17)]

            def S(i):
                h, kp = steps[i]
                g, oc, h2 = h // 4, h // 2, h % 2
                for e_ in range(2):
                    kt = 2 * kp + e_
                    P.mm(psS[i % 2][:, e_ * 512:(e_ + 1) * 512], Kz[h2][:, g, kt * 128:(kt + 1) * 128], QT[:, oc, :], True, True,
                         ["Kz", ("QT", h, qb)], [f"psS{i % 2}"])

            def epi_a(h):
                o = h % 2
                P.cp("vector", osb[o][:], psO[o][:], [f"psO{o}"], [f"osb{o}"])

            def epi_b(h):
                oc, h2, o = h // 2, h % 2, h % 2
                hs = slice(h2 * 64, h2 * 64 + 64)
                P.mm(psB[:, :], sel[:, h2, :], osb[o][:], True, True, ["sel", f"osb{o}"], ["psB"])
                P.op("vector", lambda e, o=o, hs=hs: e.reciprocal(out=rb[o][hs, :], in_=psB[hs, :]), ["psB"], [f"rb{o}"])
                P.tt("gpsimd", QT[hs, oc, :], osb[o][hs, :], rb[o][hs, :], ALU.mult, [f"osb{o}", f"rb{o}"], [("QT", h, qb)])

            S(0)
            pend = {}
            for i, (h, kp) in enumerate(steps):
                g, h2, o = h // 4, h % 2, h % 2
                if i + 1 < len(steps):
                    S(i + 1)
                p_ = i % 3
                P.act(PT[p_][:], psS[i % 2][:, :], AF.Exp, [f"psS{i % 2}"], [f"PT{p_}"], scale=0.125)
                v0 = 65 + 65 * g if h2 == 0 else 1 + 65 * g
                for e_ in range(2):
                    kt = 2 * kp + e_
                    P.mm(psO[o][:, :], VA[:, kt, v0:v0 + 128], PT[p_][:, e_ * 512:(e_ + 1) * 512], kt == 0, kt == 33, [f"PT{p_}", "VA"], [f"psO{o}"])
                if kp == 16:
                    epi_a(h)
                    pend[i + 3] = h
                if i in pend:
                    epi_b(pend.pop(i))
            for k in sorted(pend):
                epi_b(pend[k])
            for oc in range(8):
                j = 0
                for c in range(8):
                    P.mm(pX[j][:, :], wo[:, c, oc * 128:(oc + 1) * 128], QT[:, c, :], c == 0, c == 7,
                         ["wo", ("QT", 2 * c, qb), ("QT", 2 * c + 1, qb)], [f"pX{j}"])
                P.stt(xt[:, oc, :], pX[j][:, :], gates[oc], xt[:, oc, :], ALU.mult, ALU.add, [f"pX{j}", "xt", "modv"], ["xt"])
            P.dma("sync", fm(xa[:, qsl]), xt[:], reads=["xt"], writes=[("xa", qb)], sem="xt")


IN_SHAPES = {
    "xin": [D, TT], "cvec": [128, 8, 2], "w_mod": [2, D, 6 * D], "b_mod": [2, 6 * D], "vecs": [128, NV, 8],
    "mlp_w1": [2, D, 4 * D], "mlp_w2": [2, 4 * D, D],
    "rwkv_wr": [D, D], "rwkv_wk": [D, D], "rwkv_wv": [D, D], "rwkv_wo": [D, D],
    "rwkv_w1": [2, D, 64], "rwkv_w2": [2, 64, D], "rwkv_a1": [2, D, 64], "rwkv_a2": [2, 64, D],
    "rwkv_g1": [D, 128], "rwkv_g2": [128, D], "lnw_st": [128, 8, 64], "lnb_st": [128, 8, 64],
    "attn_wq": [D, D], "attn_wkd": [D, 512], "attn_wv": [D, 256], "attn_wo": [D, D],
    "cosT": [128, T], "sinT": [128, T],
    "c_ident": [128, 128], "c_ones": [128, 128], "c_bones": [128, 128], "c_masks": [128, 4, 128],
    "c_perm": [128, 128], "c_rmask": [128, 256], "c_sel": [128, 2, 128],
}


class IO(dict):
    def __init__(self, nc):
        super().__init__()
        self.nc = nc
        self.used = []

    def __missing__(self, k):
        ap = self.nc.dram_tensor(k, IN_SHAPES[k], F32, kind="ExternalInput").ap()
        self[k] = ap
        self.used.append(k)
        return ap

    def scratch(self, name, shape, dtype):
        return self.nc.dram_tensor(name, list(shape), dtype, kind="Internal").ap()

    def output(self, name, shape, dtype=F32):
        return self.nc.dram_tensor(name, list(shape), dtype, kind="ExternalOutput").ap()


def build(stages="all", dbg=None):
    nc = bass.Bass("TRN2", target_bir_lowering=False)
    io = IO(nc)
    P = Prog(nc)
    G = {}
    outs = {}
    stage_init(P, io, G)
    xa = io.scratch("xa", [D, TT], F32)
    hb = io.scratch("hb", [D, TT], BF16)
    if stages == "t_mlp":
        outs["dbg_h"] = io.output("dbg_h", [D, TT], BF16)
        stage_norm(P, io, G, "n_t", io["xin"], ALL_TILES,
                   lambda ic: mod_scalars(G, 0, 1, ic)[0], lambda ic: mod_scalars(G, 0, 1, ic)[1],
                   lambda c0, tw, ic: fm(hb[:, c0:c0 + tw]), BF16)
        with P.phase("copy"):
            P.dma("sync", xa, io["xin"], writes=["xa"], sem="cpa")
            P.dma("sync", outs["dbg_h"], hb, writes=["o"], sem="cpb")
        stage_mlp(P, io, G, 0, ALL_TILES, xa, hb)
        outs["y"] = io.output("y", [D, TT])
        fin = [G["vec"][:, 4, c:c + 1] for c in range(8)]
        stage_norm(P, io, G, "final", xa, ALL_TILES, lambda ic: fin, lambda ic: None,
                   lambda c0, tw, ic: fm(outs["y"][:, c0:c0 + tw]), F32)
    if stages in ("all", "l0", "l1pre"):
        hp = io.scratch("hp", [D, 4608], F32)
        S = rw_scratch(io)
        with P.phase("zpad"):
            z = P.sb([128, 8, 64], F32)
            P.memset("vector", z[:], 0.0, ["z"])
            for k, o in enumerate((0, 64 + T, 4224, 4288 + C)):
                P.dma("sync", fm(hp[:, o:o + 64]), z[:], reads=["z"], writes=[("hpz", k)], sem=f"z{k}")

        def hdst(c0, tw, ic):
            o = 4288 if ic else 64 + c0
            return fm(hp[:, o:o + tw])

        def hbdst(c0, tw, ic):
            return fm(hb[:, c0:c0 + tw])

        def ms(l, kind, which):
            return lambda ic: mod_scalars(G, l, kind, ic)[which]

        stage_norm(P, io, G, "n_mix0", io["xin"], ALL_TILES, ms(0, 0, 0), ms(0, 0, 1), hdst, F32)
        stage_rwkv1a(P, io, G, hp, S)
        stage_rwkv1b(P, io, G, S)
        stage_rwkv2(P, io, G, S, io["xin"], xa)
        stage_norm(P, io, G, "n_mlp0", xa, ALL_TILES, ms(0, 1, 0), ms(0, 1, 1), hbdst, BF16)
        stage_mlp(P, io, G, 0, ALL_TILES, xa, hb)
        if stages == "l0":
            outs["y"] = io.output("y", [D, TT])
            with P.phase("copyout"):
                P.dma("sync", outs["y"], xa, writes=["o"], sem="cpa")
        else:
            stage_norm(P, io, G, "n_mix1", xa, ALL_TILES, ms(1, 0, 0), ms(1, 0, 1), hbdst, BF16)
            with P.scope():
                QT = io.scratch("qtd", [D, T], BF16)
                Kz = [P.ssb([128, 4, TT], BF16, f"Kz{k}") for k in range(2)]
                VA = P.ssb([128, 34, 390], BF16, "VA")
                stage_qkv(P, io, G, hb, QT, Kz, VA)
                stage_attn(P, io, G, QT, Kz, VA, xa)
            if stages == "l1pre":
                outs["y"] = io.output("y", [D, TT])
                with P.phase("copyout"):
                    P.dma("sync", outs["y"], xa, writes=["o"], sem="cpa")
            else:
                stage_norm(P, io, G, "n_mlp1", xa, LAT_TILES, ms(1, 1, 0), ms(1, 1, 1), hbdst, BF16)
                stage_mlp(P, io, G, 1, LAT_TILES, xa, hb)
                outs["y"] = io.output("y", [D, T])
                fin = [G["vec"][:, 4, c:c + 1] for c in range(8)]
                stage_norm(P, io, G, "final", xa, LAT_TILES, lambda ic: fin, lambda ic: None,
                           lambda c0, tw, ic: fm(outs["y"][:, c0:c0 + tw]), F32)
    if stages == "t_rwkv":
        hp = io.scratch("hp", [D, 4608], F32)
        S = rw_scratch(io)
        with P.phase("zpad"):
            z = P.sb([128, 8, 64], F32)
            P.memset("vector", z[:], 0.0, ["z"])
            for k, o in enumerate((0, 64 + T, 4224, 4288 + C)):
                P.dma("sync", fm(hp[:, o:o + 64]), z[:], reads=["z"], writes=[("hpz", k)], sem=f"z{k}")
        def hdst(c0, tw, ic):
            o = 4288 if ic else 64 + c0
            return fm(hp[:, o:o + tw])
        stage_norm(P, io, G, "n_mix0", io["xin"], ALL_TILES,
                   lambda ic: mod_scalars(G, 0, 0, ic)[0], lambda ic: mod_scalars(G, 0, 0, ic)[1], hdst, F32)
        stage_rwkv1(P, io, G, hp, S)
        stage_rwkv2(P, io, G, S, io["xin"], xa)
        outs["y"] = io.output("y", [D, TT])
        with P.phase("copyout"):
            P.dma("sync", outs["y"], xa, writes=["o"], sem="cpa")
    P.close()
    return nc, io.used, list(outs.keys()), P


def fmv(v):
    return np.ascontiguousarray(np.asarray(v, np.float32).reshape(8, 128).T)


def host_consts():
    c = {}
    c["c_ident"] = np.eye(128, dtype=np.float32)
    c["c_ones"] = np.ones((128, 128), np.float32)
    blk = np.zeros((128, 128), np.float32)
    blk[:64, :64] = 1
    blk[64:, 64:] = 1
    c["c_bones"] = blk
    i = np.arange(64)
    us = (i[:, None] < i[None, :]).astype(np.float32)
    ui = (i[:, None] <= i[None, :]).astype(np.float32)
    m = np.zeros((128, 4, 128), np.float32)
    for k, mk in enumerate([us, ui, us.T, ui.T]):
        m[:64, k, :64] = mk
        m[64:, k, 64:] = mk
    c["c_masks"] = m
    Pm = np.zeros((128, 128), np.float32)
    for d in range(128):
        if d % 32 < 16:
            Pm[d, d + 16] = -1.0
        else:
            Pm[d, d - 16] = 1.0
    c["c_perm"] = np.ascontiguousarray(Pm.T)
    sel = np.zeros((128, 2, 128), np.float32)
    sel[64, 0, :] = 1.0
    sel[63, 1, :] = 1.0
    c["c_sel"] = sel
    rm = np.ones((128, 256), np.float32)
    rm[:, ::64] = 0
    c["c_rmask"] = rm
    t = np.arange(T)
    row = (t // 64).astype(np.float32)
    col = (t % 64).astype(np.float32)
    freqs = (np.float32(10000.0) ** (-np.arange(0, 32, 2, dtype=np.float32) / np.float32(32))).astype(np.float32)
    ang = np.zeros((64, T), np.float32)
    for d in range(64):
        pos = row if d < 32 else col
        ang[d] = pos * freqs[d % 16]
    c["cosT"] = np.ascontiguousarray(np.concatenate([np.cos(ang), np.cos(ang)], 0).astype(np.float32))
    c["sinT"] = np.ascontiguousarray(np.concatenate([np.sin(ang), np.sin(ang)], 0).astype(np.float32))
    return c


def host_inputs(inp, b):
    f = lambda k: np.asarray(inp[k], np.float32)
    d = {}
    d["xin"] = np.ascontiguousarray(np.concatenate([f("x")[b].T, f("ctx")[b].T], axis=1))
    d["cvec"] = np.ascontiguousarray(np.stack([fmv(f("c")[b]), fmv(f("c_ctx"))], axis=-1))
    return d


def host_shared(inp):
    f = lambda k: np.asarray(inp[k], np.float32)
    s = dict(host_consts())
    s["w_mod"] = f("w_mod")
    s["b_mod"] = f("b_mod")
    vl = [f("norm_mix")[0], f("norm_mix")[1], f("norm_mlp")[0], f("norm_mlp")[1], f("final_norm")]
    vl += [f("rwkv_mu")[0, j] for j in range(6)]
    vl += [f("rwkv_w0")[0, 0], f("rwkv_w0")[0, 1], f("rwkv_a0")[0, 0], f("rwkv_a0")[0, 1]]
    vl += [f("rwkv_k_k")[0], f("rwkv_k_a")[0], np.zeros(D, np.float32), f("rwkv_r_k")[0].reshape(-1)]
    vl += [np.tile(f("attn_q_norm")[0], 16), np.tile(f("attn_k_norm")[0], 16)]
    assert len(vl) == NV
    s["vecs"] = np.ascontiguousarray(np.stack([fmv(v) for v in vl], axis=1))
    s["mlp_w1"] = f("mlp_w1")
    s["mlp_w2"] = f("mlp_w2")
    for k in ("wr", "wk", "wv", "wo", "w1", "w2", "a1", "a2", "g1", "g2"):
        s["rwkv_" + k] = f("rwkv_" + k)[0]
    lw = f("rwkv_ln_w")[0].reshape(8, 2, 64)
    lb = f("rwkv_ln_b")[0].reshape(8, 2, 64)
    s["lnw_st"] = np.ascontiguousarray(np.repeat(lw.transpose(1, 0, 2), 64, axis=0))
    s["lnb_st"] = np.ascontiguousarray(np.repeat(lb.transpose(1, 0, 2), 64, axis=0))
    wqkv = f("attn_wqkv")[0]
    s["attn_wq"] = np.ascontiguousarray(wqkv[:, :1024])
    wk = wqkv[:, 1024:1280].reshape(D, 4, 64)
    s["attn_wkd"] = np.ascontiguousarray(np.concatenate([wk, wk], axis=2).reshape(D, 512))
    s["attn_wv"] = np.ascontiguousarray(wqkv[:, 1280:1536])
    s["attn_wo"] = f("attn_wo")[0]
    return s


_CACHE = {}


def kernel(**inputs):
    if "prog" not in _CACHE:
        _CACHE["prog"] = build("all")
    nc, used, outnames, _ = _CACHE["prog"]
    shared = host_shared(inputs)
    in_maps = []
    for b in range(NCORES):
        hi = host_inputs(inputs, b)
        hi.update(shared)
        in_maps.append({k: hi[k] for k in used})
    res = run_bass_kernel_spmd(nc, in_maps, core_ids=list(range(NCORES)))
    out = np.stack([np.ascontiguousarray(res.results[b]["y"].T) for b in range(NCORES)], axis=0)
    return out.astype(np.float32)
```

```python
from contextlib import ExitStack, contextmanager
import re as re_mod
import numpy as np
import concourse.bass as bass
import concourse.mybir as mybir
from concourse.bass_utils import run_bass_kernel_spmd

F32 = mybir.dt.float32
BF16 = mybir.dt.bfloat16
AF = mybir.ActivationFunctionType
ALU = mybir.AluOpType
AX = mybir.AxisListType

D = 1024
T = 4096
C = 256
TT = T + C
NCORES = 8
C0 = float(np.exp(-0.5))
NV = 21
ENGS = ("tensor", "vector", "scalar", "gpsimd", "sync")


class Prog:
    def __init__(self, nc):
        self.nc = nc
        self.ges = ExitStack()
        self.sems = {}
        self.cnt = {}
        self.dpool = {False: [], True: []}
        self.seen = {e: {} for e in ENGS}
        self.n = 0
        self.pes = None
        self.total_ops = 0

    def _alloc(self, es, fn, shape, dtype, name):
        self.n += 1
        return es.enter_context(fn(name or f"t{self.n}", list(shape), dtype))

    def gsb(self, shape, dtype, name=None):
        return self._alloc(self.ges, self.nc.sbuf_tensor, shape, dtype, name)

    def sb(self, shape, dtype, name=None):
        return self._alloc(self.pes, self.nc.sbuf_tensor, shape, dtype, name)

    @contextmanager
    def scope(self):
        self.ses = ExitStack()
        yield self
        self.ses.close()
        self.ses = None

    def ssb(self, shape, dtype, name=None):
        return self._alloc(self.ses, self.nc.sbuf_tensor, shape, dtype, name)

    def ps(self, shape, dtype, name=None):
        return self._alloc(self.pes, self.nc.psum_tensor, shape, dtype, name)

    @contextmanager
    def phase(self, name):
        self.ops = []
        self.last_w = {}
        self.readers = {}
        self.last_dma = {}
        self.pes = ExitStack()
        self.pname = name
        yield self
        self._emit()
        self.pes.close()
        self.pes = None

    _PSUM_RE = re_mod.compile(r"^(pp|pa|pb|pq|pf|ps\w*|pX|py|pS|ptr|pw)\d*$")

    ns = None
    ns_set = frozenset()

    def _deps(self, reads, writes):
        if self.ns is not None:
            reads = tuple((r, self.ns) if r in self.ns_set else r for r in reads)
            writes = tuple((w, self.ns) if w in self.ns_set else w for w in writes)
        extra = tuple(r for r in reads if isinstance(r, str) and self._PSUM_RE.match(r) and r not in writes)
        if extra:
            writes = tuple(writes) + extra
        deps = {}
        for r in reads:
            if r in self.last_w:
                deps.setdefault(self.last_w[r], set()).add("RAW")
        for w in writes:
            if w in self.last_w:
                deps.setdefault(self.last_w[w], set()).add("WAW")
            for rd in self.readers.get(w, ()):
                deps.setdefault(rd, set()).add("WAR")
        idx = len(self.ops)
        for r in reads:
            self.readers.setdefault(r, []).append(idx)
        for w in writes:
            self.last_w[w] = idx
            self.readers[w] = []
        return deps

    def op(self, eng, fn, reads=(), writes=()):
        deps = self._deps(tuple(reads), tuple(writes))
        self.ops.append(dict(eng=eng, fn=fn, deps=deps, dma=None))
        return len(self.ops) - 1

    def dma(self, queue, out, in_, reads=(), writes=(), sem=None):
        deps = self._deps(tuple(reads), tuple(writes))
        prev = self.last_dma.get(sem)
        if prev is not None:
            deps.setdefault(prev, set()).add("SER")
        idx = len(self.ops)
        self.last_dma[sem] = idx
        self.ops.append(dict(eng=queue, fn=lambda e: e.dma_start(out=out, in_=in_), deps=deps, dma=sem))
        return idx

    def _emit(self):
        nc = self.nc
        ops = self.ops
        if self.last_dma:
            ops.append(dict(eng="sync", fn=None, deps={i: {"FIN"} for i in self.last_dma.values()}, dma=None))
        self.total_ops += len(ops)

        def needs_wait(x, d, kinds):
            if d["dma"] is not None or x["dma"] is not None:
                return True
            if d["eng"] != x["eng"]:
                return True
            if x["eng"] == "tensor":
                return False
            return bool(kinds & {"RAW", "FIN"})

        signal = [False] * len(ops)
        for x in ops:
            for di, kinds in x["deps"].items():
                d = ops[di]
                if d["dma"] is None and needs_wait(x, d, kinds):
                    signal[di] = True
        dkeys = {}
        nk = {False: 0, True: 0}
        for o in ops:
            if o["dma"] is not None and o["dma"] not in dkeys:
                sw = o["eng"] == "gpsimd"
                dkeys[o["dma"]] = (sw, nk[sw])
                nk[sw] += 1
        for sw in (False, True):
            while len(self.dpool[sw]) < nk[sw]:
                h = self.ges.enter_context(nc.semaphore(f"dq{int(sw)}_{len(self.dpool[sw])}"))
                self.dpool[sw].append([h, 0])
        for e in ENGS:
            if e not in self.sems:
                self.sems[e] = self.ges.enter_context(nc.semaphore(f"e_{e}"))
        token = [None] * len(ops)
        for i, o in enumerate(ops):
            if o["dma"] is not None:
                dk = dkeys[o["dma"]]
                slot = self.dpool[dk[0]][dk[1]]
                slot[1] += 16
                token[i] = (("d", dk), slot[1])
            elif signal[i]:
                self.cnt[o["eng"]] = self.cnt.get(o["eng"], 0) + 1
                token[i] = (("e", o["eng"]), self.cnt[o["eng"]])
        per_eng = {e: [] for e in ENGS}
        for i, o in enumerate(ops):
            per_eng[o["eng"]].append(i)

        def semh(key):
            return self.dpool[key[1][0]][key[1][1]][0] if key[0] == "d" else self.sems[key[1]]

        def run(engname, eng):
            seen = self.seen[engname]
            for i in per_eng[engname]:
                o = ops[i]
                waits = {}
                for di, kinds in o["deps"].items():
                    d = ops[di]
                    if not needs_wait(o, d, kinds):
                        continue
                    key, val = token[di]
                    if waits.get(key, 0) < val:
                        waits[key] = val
                for key, val in waits.items():
                    if seen.get(key, 0) >= val:
                        continue
                    seen[key] = val
                    eng.wait_ge(semh(key), val)
                if o["fn"] is None:
                    continue
                ins = o["fn"](eng)
                if o["dma"] is not None:
                    ins.then_inc(semh(token[i][0]), 16)
                elif signal[i]:
                    ins.then_inc(self.sems[engname], 1)

        with nc.Block() as block:
            @block.sync
            def _(e):
                run("sync", e)

            @block.tensor
            def _(e):
                run("tensor", e)

            @block.vector
            def _(e):
                run("vector", e)

            @block.scalar
            def _(e):
                run("scalar", e)

            @block.gpsimd
            def _(e):
                run("gpsimd", e)

    def close(self):
        self.ges.close()

    def mm(self, out, lhsT, rhs, start, stop, r, w):
        self.op("tensor", lambda e: e.matmul(out, lhsT=lhsT, rhs=rhs, start=start, stop=stop), r, w)

    def tr(self, out, in_, ident, r, w):
        self.op("tensor", lambda e: e.transpose(out, in_, ident), r, w)

    def tt(self, eng, out, in0, in1, op, r, w):
        self.op(eng, lambda e: e.tensor_tensor(out=out, in0=in0, in1=in1, op=op), r, w)

    def ts(self, eng, out, in0, s1, s2, op0, op1, r, w):
        if op1 is None:
            self.op(eng, lambda e: e.tensor_scalar(out=out, in0=in0, scalar1=s1, scalar2=None, op0=op0), r, w)
        else:
            self.op(eng, lambda e: e.tensor_scalar(out=out, in0=in0, scalar1=s1, scalar2=s2, op0=op0, op1=op1), r, w)

    def stt(self, out, in0, scalar, in1, op0, op1, r, w):
        self.op("vector", lambda e: e.scalar_tensor_tensor(out=out, in0=in0, scalar=scalar, in1=in1, op0=op0, op1=op1), r, w)

    def act(self, out, in_, func, r, w, bias=None, scale=None):
        kw = {}
        if bias is not None:
            kw["bias"] = bias
        if scale is not None:
            kw["scale"] = scale
        self.op("scalar", lambda e: e.activation(out=out, in_=in_, func=func, **kw), r, w)

    def cp(self, eng, out, in_, r, w):
        if eng == "scalar":
            self.op(eng, lambda e: e.activation(out=out, in_=in_, func=AF.Copy), r, w)
        else:
            self.op(eng, lambda e: e.tensor_copy(out=out, in_=in_), r, w)

    def memset(self, eng, ap, val, w):
        self.op(eng, lambda e: e.memset(ap, val), (), w)


def fm(ap2d):
    return ap2d.rearrange("(c p) n -> p c n", p=128)


LAT_TILES = [(i * 512, 512, False) for i in range(8)]
ALL_TILES = LAT_TILES + [(T, 256, True)]


def stage_init(P, io, G):
    nc = P.nc
    G["identf"] = P.gsb([128, 128], F32, "identf")
    G["identb"] = P.gsb([128, 128], BF16, "identb")
    G["onesb"] = P.gsb([128, 128], BF16, "onesb")
    G["bones"] = P.gsb([128, 128], BF16, "bones")
    G["masks"] = P.gsb([128, 4, 128], BF16, "masks")
    G["perm"] = P.gsb([128, 128], BF16, "perm")
    G["rmask"] = P.gsb([128, 256], F32, "rmask")
    G["vec"] = P.gsb([128, NV, 8], F32, "vec")
    G["modv"] = P.gsb([128, 2, 6, 8, 2], F32, "modv")
    G["gg"] = P.gsb([128, 2, 2, 8, 2], F32, "gg")
    with P.phase("init"):
        P.dma("sync", G["identf"][:], io["c_ident"], writes=["identf"], sem="identf")
        P.dma("sync", G["rmask"][:], io["c_rmask"], writes=["rmask"], sem="rmask")
        P.dma("sync", G["vec"][:], io["vecs"], writes=["vec"], sem="vec")
        P.dma("gpsimd", G["identb"][:], io["c_ident"], writes=["identb"], sem="identb")
        P.dma("gpsimd", G["onesb"][:], io["c_ones"], writes=["onesb"], sem="onesb")
        P.dma("gpsimd", G["bones"][:], io["c_bones"], writes=["bones"], sem="bones")
        P.dma("gpsimd", G["masks"][:], io["c_masks"], writes=["masks"], sem="masks")
        P.dma("gpsimd", G["perm"][:], io["c_perm"], writes=["perm"], sem="perm")
        vec = G["vec"]
        P.ts("vector", vec[:, 17, :], vec[:, 16, :], -1.0, 1.0, ALU.mult, ALU.add, ["vec"], ["vec"])
        sv = P.sb([128, 8, 2], F32)
        svs = P.sb([128, 8, 2], F32)
        P.dma("sync", sv[:], io["cvec"], writes=["sv"], sem="sv")
        P.act(svs[:], sv[:], AF.Silu, ["sv"], ["svs"])
        brow = P.sb([2, 2 * 6144], F32)
        row = P.sb([2, 2 * 6144], F32)
        P.dma("sync", brow[:], io["b_mod"].rearrange("l n -> (l n)").partition_broadcast(2), writes=["brow"], sem="brow")
        wt = [P.sb([128, 8, 512], F32) for _ in range(2)]
        psr = [P.ps([128, 512], F32) for _ in range(2)]
        pst = P.ps([128, 512], F32)
        k = 0
        for l in range(2):
            for nb in range(12):
                b = k % 2
                k += 1
                P.dma("sync", wt[b][:], fm(io["w_mod"][l, :, nb * 512:(nb + 1) * 512]), writes=[f"wt{b}"], sem=f"wt{b}")
                for c in range(8):
                    P.mm(psr[b][0:2, :], svs[:, c, :], wt[b][:, c, :], c == 0, c == 7, ["svs", f"wt{b}"], [f"psr{b}"])
                o = l * 6144 + nb * 512
                P.tt("vector", row[:, o:o + 512], psr[b][0:2, :], brow[:, o:o + 512], ALU.add, [f"psr{b}", "brow"], ["row"])
        for l in range(2):
            for blk in range(48):
                o = l * 6144 + blk * 128
                P.tr(pst[:, l * 96 + blk * 2:l * 96 + blk * 2 + 2], row[0:2, o:o + 128], G["identf"][0:2, 0:2], ["row", "identf"], ["pst"])
        P.cp("vector", G["modv"][:].rearrange("p l m c j -> p (l m c j)"), pst[:, 0:192], ["pst"], ["modv"])
        modv, gg = G["modv"], G["gg"]
        for l in range(2):
            for kind in range(2):
                sc = modv[:, l, 1 + 3 * kind, :, :]
                nv = vec[:, (0 if kind == 0 else 2) + l, :].unsqueeze(2).broadcast_to([128, 8, 2])
                P.ts("vector", gg[:, l, kind, :, :], sc, 1.0, None, ALU.add, None, ["modv"], ["gg"])
                P.tt("vector", gg[:, l, kind, :, :], gg[:, l, kind, :, :], nv, ALU.mult, ["gg", "vec"], ["gg"])


def mod_scalars(G, l, kind, isctx):
    j = 1 if isctx else 0
    gains = [G["gg"][:, l, kind, c, j:j + 1] for c in range(8)]
    shifts = [G["modv"][:, l, 3 * kind, c, j:j + 1] for c in range(8)]
    gates = [G["modv"][:, l, 3 * kind + 2, c, j:j + 1] for c in range(8)]
    return gains, shifts, gates


def stage_norm(P, io, G, name, src, tiles, gains_fn, shifts_fn, dst_fn, out_dtype):
    with P.phase(name):
        xt = [P.sb([128, 8, 512], F32) for _ in range(2)]
        sq = P.sb([128, 8, 512], BF16)
        lnv = P.sb([128, 512], F32)
        rstd = P.sb([128, 512], F32)
        tmp = [P.sb([128, 512], F32) for _ in range(2)]
        ho = [P.sb([128, 8, 512], out_dtype) for _ in range(2)]
        ps = [P.ps([128, 512], F32) for _ in range(2)]

        def load(i):
            c0, tw, _ = tiles[i]
            b = i % 2
            P.dma("sync", xt[b][:, :, :tw], fm(src[:, c0:c0 + tw]), writes=[f"xt{b}"], sem=f"xt{b}")

        load(0)
        for i, (c0, tw, isctx) in enumerate(tiles):
            b = i % 2
            if i + 1 < len(tiles):
                load(i + 1)
            gains = gains_fn(isctx)
            shifts = shifts_fn(isctx)
            P.act(sq[:, :, :tw], xt[b][:, :, :tw], AF.Square, [f"xt{b}"], ["sq"])
            for c in range(8):
                P.mm(ps[b][:, :tw], G["onesb"][:], sq[:, c, :tw], c == 0, c == 7, ["sq", "onesb"], [f"ps{b}"])
            P.act(lnv[:, :tw], ps[b][:, :tw], AF.Ln, [f"ps{b}"], ["lnv"], bias=1e-6, scale=1.0 / D)
            P.act(rstd[:, :tw], lnv[:, :tw], AF.Exp, ["lnv"], ["rstd"], scale=-0.5)
            for c in range(8):
                if shifts is None:
                    P.stt(ho[b][:, c, :tw], xt[b][:, c, :tw], gains[c], rstd[:, :tw], ALU.mult, ALU.mult,
                          [f"xt{b}", "rstd", "vec", "gg"], [f"ho{b}"])
                else:
                    t = tmp[c % 2]
                    P.stt(t[:, :tw], xt[b][:, c, :tw], gains[c], rstd[:, :tw], ALU.mult, ALU.mult,
                          [f"xt{b}", "rstd", "vec", "gg"], [f"tmp{c % 2}"])
                    P.act(ho[b][:, c, :tw], t[:, :tw], AF.Identity, [f"tmp{c % 2}", "modv"], [f"ho{b}"], bias=shifts[c])
            P.dma("sync", dst_fn(c0, tw, isctx), ho[b][:, :, :tw], reads=[f"ho{b}"], writes=[("dst", i)], sem=f"ho{b}")


def stage_mlp(P, io, G, l, tiles, xa, hb):
    for half in range(2):
        with P.phase(f"mlp{l}{half}"):
            w1 = P.sb([128, 8, 2048], BF16)
            w2 = P.sb([128, 16, 1024], BF16)
            for q in range(2):
                P.dma("gpsimd", w1[:, :, q * 1024:(q + 1) * 1024],
                      fm(io["mlp_w1"][l, :, half * 2048 + q * 1024: half * 2048 + (q + 1) * 1024]), writes=["w1"], sem=f"w1{q}")
                P.dma("gpsimd", w2[:, q * 8:(q + 1) * 8, :],
                      io["mlp_w2"][l, half * 2048 + q * 1024: half * 2048 + (q + 1) * 1024, :].rearrange("(f p) n -> p f n", p=128),
                      writes=["w2"], sem=f"w2{q}")
            xt = [P.sb([128, 8, 512], F32) for _ in range(2)]
            ht = [P.sb([128, 8, 512], BF16) for _ in range(2)]
            h1 = P.sb([128, 16, 512], BF16)
            r1 = [P.sb([128, 512], F32) for _ in range(2)]
            ps = [P.ps([128, 512], F32) for _ in range(4)]

            def load(i):
                c0, tw, _ = tiles[i]
                b = i % 2
                P.dma("sync", ht[b][:, :, :tw], fm(hb[:, c0:c0 + tw]), writes=[f"ht{b}"], sem=f"ht{b}")
                P.dma("sync", xt[b][:, :, :tw], fm(xa[:, c0:c0 + tw]), reads=[("xa", i)], writes=[f"xt{b}"], sem=f"xt{b}")

            load(0)
            for i, (c0, tw, isctx) in enumerate(tiles):
                b = i % 2
                if i + 1 < len(tiles):
                    load(i + 1)
                _, _, gates = mod_scalars(G, l, 1, isctx)
                for fc in range(16):
                    pb = fc % 2
                    for c in range(8):
                        P.mm(ps[pb][:, :tw], w1[:, c, fc * 128:(fc + 1) * 128], ht[b][:, c, :tw], c == 0, c == 7,
                             ["w1", f"ht{b}"], [f"ps{pb}"])
                    P.act(r1[pb][:, :tw], ps[pb][:, :tw], AF.Relu, [f"ps{pb}"], [f"r1{pb}"])
                    P.tt("gpsimd", h1[:, fc, :tw], r1[pb][:, :tw], r1[pb][:, :tw], ALU.mult, [f"r1{pb}"], [("h1", fc)])
                for oc in range(8):
                    pb = 2 + oc % 2
                    for fc in range(16):
                        P.mm(ps[pb][:, :tw], w2[:, fc, oc * 128:(oc + 1) * 128], h1[:, fc, :tw], fc == 0, fc == 15,
                             ["w2", ("h1", fc)], [f"ps{pb}"])
                    P.stt(xt[b][:, oc, :tw], ps[pb][:, :tw], gates[oc], xt[b][:, oc, :tw], ALU.mult, ALU.add,
                          [f"ps{pb}", f"xt{b}", "modv"], [f"xt{b}"])
                P.dma("sync", fm(xa[:, c0:c0 + tw]), xt[b][:, :, :tw], reads=[f"xt{b}"], writes=[("xa", i)], sem=f"xt{b}")


RW_ORDER1 = [(True, 0)] + [(False, i) for i in range(16)]
RW_ORDER2 = [(True, 0)] + [(False, i) for i in range(15, -1, -1)]


def rw_scratch(io):
    S = {}
    S["yp"] = io.scratch("rw_yp", [17, 8, 128, 256], F32)
    S["sadd"] = io.scratch("rw_sadd", [17, 8, 128, 256], F32)
    S["vst"] = io.scratch("rw_vst", [17, 8, 128, 256], F32)
    S["gst"] = io.scratch("rw_gst", [17, 8, 128, 256], F32)
    S["gyb"] = io.scratch("rw_gyb", [17, 8, 128, 512], BF16)
    S["gsb"] = io.scratch("rw_gsb", [17, 8, 128, 512], BF16)
    S["gamb"] = io.scratch("rw_gamb", [17, 128, 32], F32)
    S["bon"] = io.scratch("rw_bon", [17, 128, 32], F32)
    S["ops"] = io.scratch("rw_ops", [17, 8, 128, 2048], BF16)
    S["vb"] = io.scratch("rw_vb", [17, 8, 128, 256], BF16)
    S["gam"] = io.scratch("rw_gam", [17, 8, 128, 8], F32)
    return S


def stage_rwkv1(P, io, G, hp, S, dbg=None):
    vec, masks, identb, identf, bones, onesb, rmask = (G[k] for k in ("vec", "masks", "identb", "identf", "bones", "onesb", "rmask"))
    with P.phase("rwkv1"):
        wr = P.sb([128, 8, 1024], BF16)
        wk = P.sb([128, 8, 1024], BF16)
        wv = P.sb([128, 8, 1024], BF16)
        for w, nm in ((wr, "rwkv_wr"), (wk, "rwkv_wk"), (wv, "rwkv_wv")):
            P.dma("gpsimd", w[:], fm(io[nm]), writes=[nm], sem=nm)
        lw1 = P.sb([128, 8, 128], BF16)
        la1 = P.sb([128, 8, 128], BF16)
        g1 = P.sb([128, 8, 128], BF16)
        for d in range(2):
            P.dma("gpsimd", lw1[:, :, d * 64:(d + 1) * 64], io["rwkv_w1"][d].rearrange("(c p) j -> p c j", p=128), writes=["lw1"], sem=f"lw1{d}")
            P.dma("gpsimd", la1[:, :, d * 64:(d + 1) * 64], io["rwkv_a1"][d].rearrange("(c p) j -> p c j", p=128), writes=["la1"], sem=f"la1{d}")
        P.dma("gpsimd", g1[:], io["rwkv_g1"].rearrange("(c p) j -> p c j", p=128), writes=["g1"], sem="g1")
        w2s = P.sb([128, 1024], BF16)
        a2s = P.sb([128, 1024], BF16)
        g2 = P.sb([128, 1024], BF16)
        P.dma("gpsimd", w2s[:], io["rwkv_w2"].rearrange("d j f -> (d j) f"), writes=["w2s"], sem="w2s")
        P.dma("gpsimd", a2s[:], io["rwkv_a2"].rearrange("d j f -> (d j) f"), writes=["a2s"], sem="a2s")
        P.dma("gpsimd", g2[:], io["rwkv_g2"], writes=["g2"], sem="g2")

        hh = P.sb([128, 8, 384], F32)
        xx = P.sb([128, 8, 256], F32)
        xr = P.sb([128, 8, 256], BF16)
        xk = P.sb([128, 8, 256], BF16)
        xv = P.sb([128, 8, 256], BF16)
        xrot = P.sb([128, 8, 256], BF16)
        lwt = P.sb([128, 256], BF16)
        lat = P.sb([128, 256], BF16)
        sg = P.sb([128, 256], BF16)
        f32t = {}
        for nm in ("r", "k", "sw0", "sw1", "ag0", "ag1", "kq", "lnv", "rs", "kkn", "fac", "kd0", "kd1", "b0", "b1",
                   "L", "Lx", "Lb", "E1", "E2", "E3", "ks"):
            f32t[nm] = P.sb([128, 256], F32, "t_" + nm)
        sqb = P.sb([128, 256], BF16)
        RK = P.sb([128, 4, 2, 64], BF16)
        VTbd = P.sb([128, 4, 128], F32)
        GTbd = P.sb([128, 4, 128], F32)
        Vf = P.sb([128, 4, 64], F32)
        Gf = P.sb([128, 4, 64], F32)
        YPs = P.sb([128, 4, 64], F32)
        SAs = P.sb([128, 4, 64], F32)
        gamb_t = P.sb([128, 8, 4], F32)
        bon_t = P.sb([128, 8, 4], F32)
        Sf = P.sb([128, 8, 64], BF16)
        ARq = [[P.sb([128, 4, 2, 128], BF16, f"AR{q}{d}") for d in range(2)] for q in range(2)]
        KTq = [[P.sb([128, 4, 128], BF16, f"KT{q}{d}") for d in range(2)] for q in range(2)]
        BTq = [[P.sb([128, 4, 128], BF16, f"BT{q}{d}") for d in range(2)] for q in range(2)]
        Vbq = [P.sb([128, 4, 64], BF16, f"Vb{q}") for q in range(3)]
        gamq = [[P.sb([128, 4], F32, f"gam{q}{d}") for d in range(2)] for q in range(3)]
        inv = []
        for d in range(2):
            st = {}
            for nm, shp in (("Atok", [128, 4, 128]), ("Btok", [128, 4, 128]), ("MQ", [128, 4, 256]), ("MWa", [128, 4, 2, 128]),
                            ("MWb", [128, 4, 2, 128]), ("MTa", [128, 4, 128]), ("MTb", [128, 4, 128])):
                st[nm] = P.sb(shp, BF16, f"i{d}_{nm}")
            inv.append(st)
        fin = []
        for q in range(2):
            row = []
            for d in range(2):
                st = {}
                for nm, shp in (("Ktok", [128, 4, 128]), ("NP", [128, 4, 256]), ("XW", [128, 4, 256]), ("NVb", [128, 4, 64]),
                                ("GY", [128, 4, 128]), ("GS", [128, 4, 128])):
                    st[nm] = P.sb(shp, BF16, f"f{q}{d}_{nm}")
                row.append(st)
            fin.append(row)
        ppt = [P.ps([128, 512], F32) for _ in range(2)]
        pp = [t_[:, 0:256] for t_ in ppt]
        pf = P.ps([128, 512], F32)
        pb = [P.ps([128, 512], F32) for _ in range(5)]
        cnt = {"pp": 0, "pb": 0}
        nmod = {"pp": 2, "pb": 5}

        def nxt(kind):
            i = cnt[kind] % nmod[kind]
            cnt[kind] += 1
            return i

        for q in range(2):
            for d in range(2):
                P.memset("gpsimd", ARq[q][d][:], 0.0, [f"AR{q}{d}"])
                P.memset("gpsimd", KTq[q][d][:], 0.0, [f"KT{q}{d}"])
                P.memset("gpsimd", BTq[q][d][:], 0.0, [f"BT{q}{d}"])
        P.memset("gpsimd", RK[:], 0.0, ["RK"])
        P.memset("gpsimd", VTbd[:], 0.0, ["VTbd"])
        P.memset("gpsimd", GTbd[:], 0.0, ["GTbd"])
        P.memset("gpsimd", Sf[:], 0.0, [("Sf", p) for p in range(8)])

        def v3(ap):
            return ap.rearrange("p (u s) -> p u s", s=64)

        def u128(ap):
            return ap.rearrange("p (u x) -> p u x", x=128)

        def load_hh(ti):
            isctx, idx = RW_ORDER1[ti]
            off = 4288 if isctx else 64 + 256 * idx
            P.dma("sync", hh[:], fm(hp[:, off - 64: off + 320]), writes=["hh"], sem="hh")

        def proj8(w_cols_fn, xb, bn, extra_r):
            i = nxt("pp")
            for c in range(8):
                P.mm(pp[i], w_cols_fn(c), xb[:, c, :], c == 0, c == 7, [(bn, c)] + extra_r, [f"pp{i}"])
            return i

        def tprep(ti):
            isctx, idx = RW_ORDER1[ti]
            hc = hh[:, :, 64:320]
            XXW = [("xx", c) for c in range(8)]
            if not isctx:
                h4 = hh[:, :, 64:320].rearrange("p c (r w) -> p c r w", w=64)
                x4 = xx[:].rearrange("p c (r w) -> p c r w", w=64)
                P.tt("vector", x4[:, 0:2, :, 1:64], h4[:, 0:2, :, 0:63], h4[:, 0:2, :, 1:64], ALU.subtract, ["hh"], XXW[0:2])
                P.ts("gpsimd", x4[:, 0:2, :, 0:1], h4[:, 0:2, :, 0:1], -1.0, 0.0, ALU.mult, ALU.add, ["hh"], [("xxe", 0)])
                P.tt("vector", x4[:, 2:4, :, 0:63], h4[:, 2:4, :, 1:64], h4[:, 2:4, :, 0:63], ALU.subtract, ["hh"], XXW[2:4])
                P.ts("gpsimd", x4[:, 2:4, :, 63:64], h4[:, 2:4, :, 63:64], -1.0, 0.0, ALU.mult, ALU.add, ["hh"], [("xxe", 1)])
                P.tt("gpsimd", xx[:, 4:6, :], hh[:, 4:6, 0:256], hh[:, 4:6, 64:320], ALU.subtract, ["hh"], XXW[4:6])
                P.tt("gpsimd", xx[:, 6:8, :], hh[:, 6:8, 128:384], hh[:, 6:8, 64:320], ALU.subtract, ["hh"], XXW[6:8])
            else:
                P.tt("vector", xx[:, 0:4, :], hh[:, 0:4, 63:319], hh[:, 0:4, 64:320], ALU.subtract, ["hh"], XXW[0:4] + [("xxe", 0)])
                P.tt("gpsimd", xx[:, 4:8, :], hh[:, 4:8, 65:321], hh[:, 4:8, 64:320], ALU.subtract, ["hh"], XXW[4:8] + [("xxe", 1)])
            yield

            def mk_xj(j, buf, bn):
                for c in range(8):
                    P.stt(buf[:, c, :], xx[:, c, :], vec[:, 5 + j, c:c + 1], hc[:, c, :], ALU.mult, ALU.add,
                          [("xx", c), ("xxe", 0), ("xxe", 1), "hh", "vec"], [(bn, c)])

            mk_xj(1, xrot, "xrot")
            yield
            i = proj8(lambda c: lw1[:, c, :], xrot, "xrot", ["lw1"])
            P.act(lwt[:], pp[i], AF.Tanh, [f"pp{i}"], ["lwt"])
            yield
            mk_xj(4, xrot, "xrot")
            yield
            i = proj8(lambda c: la1[:, c, :], xrot, "xrot", ["la1"])
            P.cp("scalar", lat[:], pp[i], [f"pp{i}"], ["lat"])
            yield
            mk_xj(5, xrot, "xrot")
            yield
            i = proj8(lambda c: g1[:, c, :], xrot, "xrot", ["g1"])
            P.act(sg[:], pp[i], AF.Sigmoid, [f"pp{i}"], ["sg"])
            yield
            mk_xj(0, xr, "xr")
            yield
            mk_xj(2, xk, "xk")
            yield
            mk_xj(3, xv, "xv")
            if ti + 1 < len(RW_ORDER1):
                load_hh(ti + 1)
            yield

        def prep(ti, oc, q, z):
            isctx, idx = RW_ORDER1[ti]
            tg = 16 if isctx else idx
            cs = slice(oc * 128, (oc + 1) * 128)
            t = f32t
            AR, KT, BT, Vb, gam = ARq[q], KTq[q], BTq[q], Vbq[z], gamq[z]
            i = proj8(lambda c: wr[:, c, cs], xr, "xr", ["rwkv_wr"])
            P.cp("scalar", t["r"][:], pp[i], [f"pp{i}"], ["r"])
            i = proj8(lambda c: wk[:, c, cs], xk, "xk", ["rwkv_wk"])
            P.cp("scalar", t["k"][:], pp[i], [f"pp{i}"], ["k"])
            i = proj8(lambda c: wv[:, c, cs], xv, "xv", ["rwkv_wv"])
            vt4 = VTbd[:].rearrange("p u (h s) -> p u h s", h=2)
            for h2 in range(2):
                sl = slice(h2 * 64, (h2 + 1) * 64)
                P.cp("scalar", vt4[sl, :, h2, :], v3(pp[i][sl, :]), [f"pp{i}"], ["VTbd"])
            i = nxt("pp")
            P.mm(pp[i], g2[:, cs], sg[:], True, True, ["g2", "sg"], [f"pp{i}"])
            gt4 = GTbd[:].rearrange("p u (h s) -> p u h s", h=2)
            for h2 in range(2):
                sl = slice(h2 * 64, (h2 + 1) * 64)
                P.cp("scalar", gt4[sl, :, h2, :], v3(pp[i][sl, :]), [f"pp{i}"], ["GTbd"])
            yield
            j = nxt("pb")
            for u in range(4):
                P.tr(pb[j][:, u * 128:(u + 1) * 128], VTbd[:, u, :], identf[:], ["VTbd", "identf"], [f"pb{j}"])
            pv = u128(pb[j][:])
            for h2 in range(2):
                sl = slice(h2 * 64, (h2 + 1) * 64)
                P.cp("scalar", Vf[sl, :, :], pv[sl, :, h2 * 64:(h2 + 1) * 64], [f"pb{j}"], ["Vf"])
            P.cp("gpsimd", Vb[:], Vf[:], ["Vf"], [f"Vb{z}"])
            P.dma("sync", S["vst"][tg, oc].rearrange("p (u s) -> p u s", s=64), Vf[:], reads=["Vf"], writes=[("vst", tg, oc)], sem="Vf")
            j = nxt("pb")
            for u in range(4):
                P.tr(pb[j][:, u * 128:(u + 1) * 128], GTbd[:, u, :], identf[:], ["GTbd", "identf"], [f"pb{j}"])
            pv = u128(pb[j][:])
            for h2 in range(2):
                sl = slice(h2 * 64, (h2 + 1) * 64)
                P.cp("scalar", Gf[sl, :, :], pv[sl, :, h2 * 64:(h2 + 1) * 64], [f"pb{j}"], ["Gf"])
            P.dma("sync", S["gst"][tg, oc].rearrange("p (u s) -> p u s", s=64), Gf[:], reads=["Gf"], writes=[("gst", tg, oc)], sem="Gf")
            yield
            for d in range(2):
                dl = slice(d * 64, (d + 1) * 64)
                i = nxt("pp")
                P.mm(pp[i], w2s[dl, cs], lwt[dl, :], True, True, ["w2s", "lwt"], [f"pp{i}"])
                P.act(t[f"sw{d}"][:], pp[i], AF.Sigmoid, [f"pp{i}", "vec"], [f"sw{d}"], bias=vec[:, 11 + d, oc:oc + 1])
                i = nxt("pp")
                P.mm(pp[i], a2s[dl, cs], lat[dl, :], True, True, ["a2s", "lat"], [f"pp{i}"])
                P.act(t[f"ag{d}"][:], pp[i], AF.Sigmoid, [f"pp{i}", "vec"], [f"ag{d}"], bias=vec[:, 13 + d, oc:oc + 1])
            yield
            P.ts("vector", t["kq"][:], t["k"][:], vec[:, 15, oc:oc + 1], None, ALU.mult, None, ["k", "vec"], ["kq"])
            P.act(sqb[:], t["kq"][:], AF.Square, ["kq"], ["sqb"])
            i = nxt("pp")
            P.mm(pp[i], bones[:], sqb[:], True, True, ["bones", "sqb"], [f"pp{i}"])
            P.act(t["lnv"][:], pp[i], AF.Ln, [f"pp{i}"], ["lnv"], bias=1e-12)
            P.act(t["rs"][:], t["lnv"][:], AF.Exp, ["lnv"], ["rs"], scale=-0.5)
            P.tt("gpsimd", t["kkn"][:], t["kq"][:], t["rs"][:], ALU.mult, ["kq", "rs"], ["kkn"])
            for d in range(2):
                sw, ag, kd, bb = t[f"sw{d}"], t[f"ag{d}"], t[f"kd{d}"], t[f"b{d}"]
                EE = "gpsimd" if d == 0 else "vector"
                P.ts(EE, t["fac"][:], ag[:], vec[:, 16, oc:oc + 1], vec[:, 17, oc:oc + 1], ALU.mult, ALU.add, [f"ag{d}", "vec"], ["fac"])
                P.tt(EE, kd[:], t["k"][:], t["fac"][:], ALU.mult, ["k", "fac"], [f"kd{d}"])
                P.tt(EE, bb[:], t["kkn"][:], ag[:], ALU.mult, ["kkn", f"ag{d}"], [f"b{d}"])
                P.op("vector", lambda e, sw=sw: e.tensor_tensor_scan(out=t["L"][:], data0=rmask[:], data1=sw[:], initial=0.0,
                                                                      op0=ALU.mult, op1=ALU.add), [f"sw{d}", "rmask"], ["L"])
                L3 = v3(t["L"][:])
                if d == 0:
                    P.tt(EE, t["Lx"][:], t["L"][:], sw[:], ALU.subtract, ["L", f"sw{d}"], ["Lx"])
                    Li, Lin = t["L"], "L"
                else:
                    P.tt(EE, v3(t["Lx"][:]), L3[:, :, 63:64].broadcast_to([128, 4, 64]), L3, ALU.subtract, ["L"], ["Lx"])
                    P.tt(EE, t["Lb"][:], t["Lx"][:], sw[:], ALU.add, ["Lx", f"sw{d}"], ["Lb"])
                    Li, Lin = t["Lb"], "Lb"
                P.act(t["E1"][:], Li[:], AF.Exp, [Lin], ["E1"], scale=-C0)
                P.act(t["E3"][:], Li[:], AF.Exp, [Lin], ["E3"], scale=C0)
                P.act(t["E2"][:], t["Lx"][:], AF.Exp, ["Lx"], ["E2"], scale=-C0)
                ar5 = AR[d][:].rearrange("p u a (h s) -> p u a h s", h=2)
                kt4 = KT[d][:].rearrange("p u (h s) -> p u h s", h=2)
                bt4 = BT[d][:].rearrange("p u (h s) -> p u h s", h=2)
                for h2 in range(2):
                    sl = slice(h2 * 64, (h2 + 1) * 64)
                    P.stt(ar5[sl, :, 0, h2, :], v3(t["kkn"][sl, :]), -1.0, v3(t["E2"][sl, :]), ALU.mult, ALU.mult, ["kkn", "E2"], [f"AR{q}{d}"])
                    P.tt(EE, ar5[sl, :, 1, h2, :], v3(t["r"][sl, :]), v3(t["E1"][sl, :]), ALU.mult, ["r", "E1"], [f"AR{q}{d}"])
                    P.tt(EE, kt4[sl, :, h2, :], v3(kd[sl, :]), v3(t["E3"][sl, :]), ALU.mult, [f"kd{d}", "E3"], [f"KT{q}{d}"])
                    P.tt(EE, bt4[sl, :, h2, :], v3(bb[sl, :]), v3(t["E3"][sl, :]), ALU.mult, [f"b{d}", "E3"], [f"BT{q}{d}"])
                E13 = v3(t["E1"][:])
                gsrc = E13[:, :, 63] if d == 0 else E13[:, :, 0]
                P.cp("vector", gam[d][:], gsrc, ["E1"], [f"gam{z}{d}"])
                if d == 1:
                    P.cp("gpsimd", gamb_t[:, oc, :], gam[1][:], [f"gam{z}1"], ["gamb_t"])
                yield
            P.tt("gpsimd", t["ks"][:], t["kd0"][:], t["kd1"][:], ALU.add, ["kd0", "kd1"], ["ks"])
            for h2 in range(2):
                sl = slice(h2 * 64, (h2 + 1) * 64)
                P.stt(RK[sl, :, h2, :], v3(t["r"][sl, :]), vec[sl, 18, oc:oc + 1], v3(t["ks"][sl, :]), ALU.mult, ALU.mult, ["r", "ks", "vec"], ["RK"])
            i = nxt("pp")
            for u in range(4):
                P.mm(pp[i][:, u:u + 1], RK[:, u, :, :].rearrange("p h s -> p (h s)"), onesb[:, 0:1], True, True, ["RK", "onesb"], [f"pp{i}"])
            P.cp("scalar", bon_t[:, oc, :], pp[i][:, 0:4], [f"pp{i}"], ["bon_t"])
            if oc == 7:
                P.dma("sync", S["gamb"][tg], gamb_t[:].rearrange("p a b -> p (a b)"), reads=["gamb_t"], writes=[("gamb", tg)], sem="gamb_t")
                P.dma("sync", S["bon"][tg], bon_t[:].rearrange("p a b -> p (a b)"), reads=["bon_t"], writes=[("bon", tg)], sem="bon_t")
            yield

        def chain(ti, oc, q, d, z):
            AR, KT, BT, Vb = ARq[q][d], KTq[q][d], BTq[q][d], Vbq[z]
            ARn, KTn, BTn, Vbn = f"AR{q}{d}", f"KT{q}{d}", f"BT{q}{d}", f"Vb{z}"
            iv, fn = inv[d], fin[q][d]
            IR = lambda nm: f"i{d}_{nm}"
            FR = lambda nm: f"f{q}{d}_{nm}"
            mS, mC = (0, 2) if d == 0 else (2, 0)
            mSI = masks[:, mS:mS + 2, :].rearrange("p a b -> p (a b)").unsqueeze(1).broadcast_to([128, 4, 256])
            mCb = masks[:, mC, :].unsqueeze(1).broadcast_to([128, 4, 128])
            idb = identb[:].unsqueeze(1).broadcast_to([128, 4, 128])
            for src, srcn, dst, dstn in ((AR[:, :, 0, :], ARn, iv["Atok"], IR("Atok")), (BT[:], BTn, iv["Btok"], IR("Btok")),
                                         (KT[:], KTn, fn["Ktok"], FR("Ktok"))):
                j = nxt("pb")
                pbt = pb[j][:].bitcast(BF16)
                for u in range(4):
                    P.tr(pbt[:, u * 128:(u + 1) * 128], src[:, u, :], identb[:], [srcn, "identb"], [f"pb{j}"])
                P.cp("scalar", dst[:].rearrange("p u x -> p (u x)"), pbt[:, 0:512], [f"pb{j}"], [dstn])
            mSb = masks[:, mS, :].unsqueeze(1).broadcast_to([128, 4, 128])
            mIb = masks[:, mS + 1, :].unsqueeze(1).broadcast_to([128, 4, 128])

            def two_bank(mm_fn):
                j0, j1 = nxt("pb"), nxt("pb")
                for u in range(4):
                    mm_fn(u, pb[j0][:, u * 128:(u + 1) * 128], f"pb{j0}", pb[j1][:, u * 128:(u + 1) * 128], f"pb{j1}")
                return j0, j1

            for lhs, lhsn, dst, dstn in ((BT, BTn, iv["MQ"], IR("MQ")), (KT, KTn, fn["NP"], FR("NP"))):
                def mm_ab(u, o0, n0, o1, n1, lhs=lhs, lhsn=lhsn):
                    P.mm(o0, lhs[:, u, :], AR[:, u, 0, :], True, True, [lhsn, ARn], [n0])
                    P.mm(o1, lhs[:, u, :], AR[:, u, 1, :], True, True, [lhsn, ARn], [n1])
                j0, j1 = two_bank(mm_ab)
                P.tt("vector", dst[:, :, 0:128], u128(pb[j0][:]), mSb, ALU.mult, [f"pb{j0}", "masks"], [dstn])
                P.tt("vector", dst[:, :, 128:256], u128(pb[j1][:]), mIb, ALU.mult, [f"pb{j1}", "masks"], [dstn])
            j = nxt("pb")
            for u in range(4):
                P.mm(pb[j][:, u * 128:(u + 1) * 128], AR[:, u, 0, :], BT[:, u, :], True, True, [ARn, BTn], [f"pb{j}"])
            cur, curn, nx, nxn = iv["MWa"], IR("MWa"), iv["MWb"], IR("MWb")
            P.tt("vector", cur[:, :, 0, :], u128(pb[j][:]), mCb, ALU.mult, [f"pb{j}", "masks"], [curn])
            yield
            j = nxt("pb")
            for u in range(4):
                P.mm(pb[j][:, u * 128:(u + 1) * 128], iv["MQ"][:, u, 0:128], cur[:, u, 0, :], True, True, [IR("MQ"), curn], [f"pb{j}"])
            P.cp("scalar", nx[:, :, 0, :], u128(pb[j][:]), [f"pb{j}"], [nxn])
            P.tt("gpsimd", nx[:, :, 1, :], cur[:, :, 0, :], idb, ALU.add, [curn, "identb"], [nxn])
            j = nxt("pb")
            for u in range(4):
                P.mm(pb[j][:, u * 128:(u + 1) * 128], cur[:, u, 0, :], iv["MQ"][:, u, 0:128], True, True, [IR("MQ"), curn], [f"pb{j}"])
            curT, curTn, nxT, nxTn = iv["MTa"], IR("MTa"), iv["MTb"], IR("MTb")
            P.cp("scalar", curT[:], u128(pb[j][:]), [f"pb{j}"], [curTn])
            cur, curn, nx, nxn = nx, nxn, cur, curn
            yield
            for lev in range(1, 5):
                def mm_lev(u, o0, n0, o1, n1, cur=cur, curn=curn, curT=curT, curTn=curTn):
                    P.mm(o0, curT[:, u, :], cur[:, u, 0, :], True, True, [curTn, curn], [n0])
                    P.mm(o1, curT[:, u, :], cur[:, u, 1, :], True, True, [curTn, curn], [n1])
                j0, j1 = two_bank(mm_lev)
                P.cp("scalar", nx[:, :, 0, :], u128(pb[j0][:]), [f"pb{j0}"], [nxn])
                P.tt("vector", nx[:, :, 1, :], u128(pb[j1][:]), cur[:, :, 1, :], ALU.add, [f"pb{j1}", curn], [nxn])
                j = nxt("pb")
                for u in range(4):
                    P.mm(pb[j][:, u * 128:(u + 1) * 128], cur[:, u, 0, :], curT[:, u, :], True, True, [curn, curTn], [f"pb{j}"])
                P.cp("scalar", nxT[:], u128(pb[j][:]), [f"pb{j}"], [nxTn])
                cur, curn, nx, nxn = nx, nxn, cur, curn
                curT, curTn, nxT, nxTn = nxT, nxTn, curT, curTn
                yield
            j = nxt("pb")
            for u in range(4):
                P.mm(pb[j][:, u * 128:(u + 1) * 128], curT[:, u, :], cur[:, u, 1, :], True, True, [curTn, curn], [f"pb{j}"])
            P.tt("vector", nx[:, :, 1, :], u128(pb[j][:]), cur[:, :, 1, :], ALU.add, [f"pb{j}", curn], [nxn])
            W6, W6n = nx, nxn
            j = nxt("pb")
            for u in range(4):
                P.mm(pb[j][:, u * 64:(u + 1) * 64], fn["NP"][:, u, 0:128], Vb[:, u, :], True, True, [FR("NP"), Vbn], [f"pb{j}"])
            P.cp("scalar", fn["NVb"][:].rearrange("p u x -> p (u x)"), pb[j][:, 0:256], [f"pb{j}"], [FR("NVb")])
            yield

            def mm_d(u, o0, n0, o1, n1):
                P.mm(o0, W6[:, u, 1, :], iv["MQ"][:, u, 128:256], True, True, [W6n, IR("MQ")], [n0])
                P.mm(o1, W6[:, u, 1, :], iv["Btok"][:, u, :], True, True, [W6n, IR("Btok")], [n1])
            j0, j1 = two_bank(mm_d)
            P.cp("scalar", fn["XW"][:, :, 0:128], u128(pb[j0][:]), [f"pb{j0}"], [FR("XW")])
            P.cp("vector", fn["XW"][:, :, 128:256], u128(pb[j1][:]), [f"pb{j1}"], [FR("XW")])
            yield

            def mm_f(u, o0, n0, o1, n1):
                P.mm(o0, iv["Atok"][:, u, :], fn["XW"][:, u, 0:128], True, True, [IR("Atok"), FR("XW")], [n0])
                P.mm(o1, iv["Atok"][:, u, :], fn["XW"][:, u, 128:256], True, True, [IR("Atok"), FR("XW")], [n1])
            j0, j1 = two_bank(mm_f)
            P.tt("vector", fn["GY"][:], u128(pb[j0][:]), AR[:, :, 1, :], ALU.add, [f"pb{j0}", ARn], [FR("GY")])
            P.tt("vector", fn["GS"][:], u128(pb[j1][:]), idb, ALU.add, [f"pb{j1}", "identb"], [FR("GS")])
            yield

        def finish(ti, oc, q, z):
            isctx, idx = RW_ORDER1[ti]
            tg = 16 if isctx else idx
            sf, sb_ = fin[q]
            F0 = lambda nm: f"f{q}0_{nm}"
            F1 = lambda nm: f"f{q}1_{nm}"
            Vb, Vbn, gam = Vbq[z], f"Vb{z}", gamq[z]
            SFR = ("Sf", oc)
            for u in range(4):
                yo = pf[:, u * 64:(u + 1) * 64]
                P.mm(yo, sf["NP"][:, u, 128:256], Vb[:, u, :], True, False, [F0("NP"), Vbn], ["pf"])
                P.mm(yo, sf["XW"][:, u, 0:128], sf["NVb"][:, u, :], False, False, [F0("XW"), F0("NVb")], ["pf"])
                P.mm(yo, sb_["NP"][:, u, 128:256], Vb[:, u, :], False, False, [F1("NP"), Vbn], ["pf"])
                P.mm(yo, sb_["XW"][:, u, 0:128], sb_["NVb"][:, u, :], False, False, [F1("XW"), F1("NVb")], ["pf"])
                P.mm(yo, sf["GY"][:, u, :], Sf[:, oc, :], False, True, [F0("GY"), SFR], ["pf"])
                so = pf[:, 256:320]
                P.mm(so, sf["Ktok"][:, u, :], Vb[:, u, :], True, False, [F0("Ktok"), Vbn], ["pf"])
                P.mm(so, sf["XW"][:, u, 128:256], sf["NVb"][:, u, :], False, False, [F0("XW"), F0("NVb")], ["pf"])
                P.mm(so, sf["GS"][:, u, :], Sf[:, oc, :], False, True, [F0("GS"), SFR], ["pf"])
                P.ts("vector", Sf[:, oc, :], so, gam[0][:, u:u + 1], None, ALU.mult, None, ["pf", f"gam{z}0"], [SFR])
                yield
            P.cp("vector", YPs[:].rearrange("p u x -> p (u x)"), pf[:, 0:256], ["pf"], ["YPs"])
            P.dma("sync", S["yp"][tg, oc], YPs[:].rearrange("p u x -> p (u x)"), reads=["YPs"], writes=[("yp", tg, oc)], sem="YPs")
            j = nxt("pb")
            for u in range(4):
                so = pb[j][:, u * 64:(u + 1) * 64]
                P.mm(so, sb_["Ktok"][:, u, :], Vb[:, u, :], True, False, [F1("Ktok"), Vbn], [f"pb{j}"])
                P.mm(so, sb_["XW"][:, u, 128:256], sb_["NVb"][:, u, :], False, True, [F1("XW"), F1("NVb")], [f"pb{j}"])
            P.cp("scalar", SAs[:].rearrange("p u x -> p (u x)"), pb[j][:, 0:256], [f"pb{j}"], ["SAs"])
            P.dma("sync", S["sadd"][tg, oc], SAs[:].rearrange("p u x -> p (u x)"), reads=["SAs"], writes=[("sadd", tg, oc)], sem="SAs")
            P.dma("sync", S["gyb"][tg, oc], sb_["GY"][:].rearrange("p u x -> p (u x)"), reads=[F1("GY")], writes=[("gyb", tg, oc)], sem=F1("GY"))
            P.dma("sync", S["gsb"][tg, oc], sb_["GS"][:].rearrange("p u x -> p (u x)"), reads=[F1("GS")], writes=[("gsb", tg, oc)], sem=F1("GS"))
            yield

        NT = len(RW_ORDER1)
        NJ = NT * 8
        done = {"prep": set(), "c0": set(), "c1": set(), "fin": set(), "tprep": set()}

        def stream_P():
            for ti in range(NT):
                yield ("tprep", ti, lambda ti=ti: (ti == 0 or ("prep", (ti - 1) * 8 + 7) in donef), lambda ti=ti: tprep(ti))
                for oc in range(8):
                    k = ti * 8 + oc
                    yield ("prep", k, lambda k=k: ((k < 2 or (("c0", k - 2) in donef and ("c1", k - 2) in donef)) and (k < 3 or ("fin", k - 3) in donef)),
                           lambda ti=ti, oc=oc, k=k: prep(ti, oc, k % 2, k % 3))

        def stream_C(d):
            for k in range(NJ):
                ti, oc = divmod(k, 8)
                yield (f"c{d}", k, lambda k=k: (("prep", k) in donef and (k < 2 or ("fin", k - 2) in donef)),
                       lambda ti=ti, oc=oc, k=k: chain(ti, oc, k % 2, d, k % 3))

        def stream_F():
            for k in range(NJ):
                ti, oc = divmod(k, 8)
                yield ("fin", k, lambda k=k: (("c0", k) in donef and ("c1", k) in donef),
                       lambda ti=ti, oc=oc, k=k: finish(ti, oc, k % 2, k % 3))

        donef = set()
        load_hh(0)
        streams = [stream_C(0), stream_C(1), stream_F(), stream_P()]
        cur = [None] * 4
        pend = [None] * 4
        alive = [True] * 4
        while any(alive):
            progressed = False
            for si in range(4):
                if not alive[si]:
                    continue
                if cur[si] is None:
                    if pend[si] is None:
                        try:
                            pend[si] = next(streams[si])
                        except StopIteration:
                            alive[si] = False
                            continue
                    kind, k, ready, mk = pend[si]
                    if not ready():
                        continue
                    cur[si] = (kind, k, mk())
                    pend[si] = None
                kind, k, gen = cur[si]
                try:
                    next(gen)
                    progressed = True
                except StopIteration:
                    donef.add((kind, k))
                    cur[si] = None
                    progressed = True
            assert progressed or not any(alive), "scheduler stuck"


def stage_rwkv1a(P, io, G, hp, S):
    vec, masks, identb, identf, bones, onesb, rmask = (G[k] for k in ("vec", "masks", "identb", "identf", "bones", "onesb", "rmask"))
    with P.phase("rwkv1a"):
        wr = P.sb([128, 8, 1024], BF16)
        wk = P.sb([128, 8, 1024], BF16)
        wv = P.sb([128, 8, 1024], BF16)
        for w, nm in ((wr, "rwkv_wr"), (wk, "rwkv_wk"), (wv, "rwkv_wv")):
            P.dma("gpsimd", w[:], fm(io[nm]), writes=[nm], sem=nm)
        lw1 = P.sb([128, 8, 128], BF16)
        la1 = P.sb([128, 8, 128], BF16)
        g1 = P.sb([128, 8, 128], BF16)
        for d in range(2):
            P.dma("gpsimd", lw1[:, :, d * 64:(d + 1) * 64], io["rwkv_w1"][d].rearrange("(c p) j -> p c j", p=128), writes=["lw1"], sem=f"lw1{d}")
            P.dma("gpsimd", la1[:, :, d * 64:(d + 1) * 64], io["rwkv_a1"][d].rearrange("(c p) j -> p c j", p=128), writes=["la1"], sem=f"la1{d}")
        P.dma("gpsimd", g1[:], io["rwkv_g1"].rearrange("(c p) j -> p c j", p=128), writes=["g1"], sem="g1")
        w2s = P.sb([128, 1024], BF16)
        a2s = P.sb([128, 1024], BF16)
        g2 = P.sb([128, 1024], BF16)
        P.dma("gpsimd", w2s[:], io["rwkv_w2"].rearrange("d j f -> (d j) f"), writes=["w2s"], sem="w2s")
        P.dma("gpsimd", a2s[:], io["rwkv_a2"].rearrange("d j f -> (d j) f"), writes=["a2s"], sem="a2s")
        P.dma("gpsimd", g2[:], io["rwkv_g2"], writes=["g2"], sem="g2")

        hh = P.sb([128, 8, 384], F32)
        xx = P.sb([128, 8, 256], F32)
        xr = P.sb([128, 8, 256], BF16)
        xk = P.sb([128, 8, 256], BF16)
        xv = P.sb([128, 8, 256], BF16)
        xrot = P.sb([128, 8, 256], BF16)
        lwt = P.sb([128, 256], BF16)
        lat = P.sb([128, 256], BF16)
        sg = P.sb([128, 256], BF16)
        NSET = 3
        bufs = []
        for w_ in range(NSET):
            B_ = {"t": {}}
            for nm in ("r", "k", "sw0", "sw1", "ag0", "ag1", "kq", "lnv", "rs", "kkn", "fac", "kd0", "kd1", "b0", "b1",
                       "L", "Lx", "Lb", "E1", "E2", "E3", "ks"):
                B_["t"][nm] = P.sb([128, 256], F32, f"t{w_}_" + nm)
            B_["sqb"] = P.sb([128, 256], BF16)
            B_["RK"] = P.sb([128, 4, 2, 64], BF16)
            B_["VTbd"] = P.sb([128, 4, 128], F32)
            B_["GTbd"] = P.sb([128, 4, 128], F32)
            B_["Vf"] = P.sb([128, 4, 64], F32)
            B_["Gf"] = P.sb([128, 4, 64], F32)
            B_["ops"] = P.sb([128, 2, 4, 256], BF16)
            B_["vb"] = P.sb([128, 4, 64], BF16)
            B_["gam"] = P.sb([128, 2, 4], F32)
            bufs.append(B_)
            P.memset("gpsimd", B_["RK"][:], 0.0, [("RK", w_)])
            P.memset("gpsimd", B_["VTbd"][:], 0.0, [("VTbd", w_)])
            P.memset("gpsimd", B_["GTbd"][:], 0.0, [("GTbd", w_)])
        P.ns_set = frozenset(["r", "k", "sw0", "sw1", "ag0", "ag1", "kq", "lnv", "rs", "kkn", "fac", "kd0", "kd1", "b0", "b1",
                              "L", "Lx", "Lb", "E1", "E2", "E3", "ks", "sqb", "RK", "VTbd", "GTbd", "Vf", "Gf", "ops_st", "vb_st", "gam_st"])
        gamb_t = P.sb([128, 8, 4], F32)
        bon_t = P.sb([128, 8, 4], F32)
        ppt = [P.ps([128, 512], F32) for _ in range(4)]
        pp = [t_[:, 0:256] for t_ in ppt]
        pb = [P.ps([128, 512], F32) for _ in range(4)]
        cnt = {"pp": 0, "pb": 0}
        nmod = {"pp": 4, "pb": 4}

        def nxt(kind):
            i = cnt[kind] % nmod[kind]
            cnt[kind] += 1
            return i

        def v3(ap):
            return ap.rearrange("p (u s) -> p u s", s=64)

        def u128(ap):
            return ap.rearrange("p (u x) -> p u x", x=128)

        def load_hh(ti):
            isctx, idx = RW_ORDER1[ti]
            off = 4288 if isctx else 64 + 256 * idx
            P.dma("sync", hh[:], fm(hp[:, off - 64: off + 320]), writes=["hh"], sem="hh")

        def proj8(w_cols_fn, xb, bn, extra_r):
            i = nxt("pp")
            for c in range(8):
                P.mm(pp[i], w_cols_fn(c), xb[:, c, :], c == 0, c == 7, [(bn, c)] + extra_r, [f"pp{i}"])
            return i

        def tprep(ti):
            isctx, idx = RW_ORDER1[ti]
            hc = hh[:, :, 64:320]
            XXW = [("xx", c) for c in range(8)]
            if not isctx:
                h4 = hh[:, :, 64:320].rearrange("p c (r w) -> p c r w", w=64)
                x4 = xx[:].rearrange("p c (r w) -> p c r w", w=64)
                P.tt("vector", x4[:, 0:2, :, 1:64], h4[:, 0:2, :, 0:63], h4[:, 0:2, :, 1:64], ALU.subtract, ["hh"], XXW[0:2])
                P.ts("gpsimd", x4[:, 0:2, :, 0:1], h4[:, 0:2, :, 0:1], -1.0, 0.0, ALU.mult, ALU.add, ["hh"], [("xxe", 0)])
                P.tt("vector", x4[:, 2:4, :, 0:63], h4[:, 2:4, :, 1:64], h4[:, 2:4, :, 0:63], ALU.subtract, ["hh"], XXW[2:4])
                P.ts("gpsimd", x4[:, 2:4, :, 63:64], h4[:, 2:4, :, 63:64], -1.0, 0.0, ALU.mult, ALU.add, ["hh"], [("xxe", 1)])
                P.tt("gpsimd", xx[:, 4:6, :], hh[:, 4:6, 0:256], hh[:, 4:6, 64:320], ALU.subtract, ["hh"], XXW[4:6])
                P.tt("gpsimd", xx[:, 6:8, :], hh[:, 6:8, 128:384], hh[:, 6:8, 64:320], ALU.subtract, ["hh"], XXW[6:8])
            else:
                P.tt("vector", xx[:, 0:4, :], hh[:, 0:4, 63:319], hh[:, 0:4, 64:320], ALU.subtract, ["hh"], XXW[0:4] + [("xxe", 0)])
                P.tt("gpsimd", xx[:, 4:8, :], hh[:, 4:8, 65:321], hh[:, 4:8, 64:320], ALU.subtract, ["hh"], XXW[4:8] + [("xxe", 1)])
            yield

            def mk_xj(j, buf, bn):
                for c in range(8):
                    P.stt(buf[:, c, :], xx[:, c, :], vec[:, 5 + j, c:c + 1], hc[:, c, :], ALU.mult, ALU.add,
                          [("xx", c), ("xxe", 0), ("xxe", 1), "hh", "vec"], [(bn, c)])

            mk_xj(1, xrot, "xrot")
            yield
            i = proj8(lambda c: lw1[:, c, :], xrot, "xrot", ["lw1"])
            P.act(lwt[:], pp[i], AF.Tanh, [f"pp{i}"], ["lwt"])
            yield
            mk_xj(4, xrot, "xrot")
            yield
            i = proj8(lambda c: la1[:, c, :], xrot, "xrot", ["la1"])
            P.cp("scalar", lat[:], pp[i], [f"pp{i}"], ["lat"])
            yield
            mk_xj(5, xrot, "xrot")
            yield
            i = proj8(lambda c: g1[:, c, :], xrot, "xrot", ["g1"])
            P.act(sg[:], pp[i], AF.Sigmoid, [f"pp{i}"], ["sg"])
            yield
            mk_xj(0, xr, "xr")
            yield
            mk_xj(2, xk, "xk")
            yield
            mk_xj(3, xv, "xv")
            if ti + 1 < len(RW_ORDER1):
                load_hh(ti + 1)
            yield

        def prep(ti, oc, w):
            isctx, idx = RW_ORDER1[ti]
            tg = 16 if isctx else idx
            cs = slice(oc * 128, (oc + 1) * 128)
            B_ = bufs[w]
            t, sqb, RK, VTbd, GTbd, Vf, Gf = B_["t"], B_["sqb"], B_["RK"], B_["VTbd"], B_["GTbd"], B_["Vf"], B_["Gf"]
            ops_st, vb_st, gam_st = B_["ops"], B_["vb"], B_["gam"]
            i = proj8(lambda c: wr[:, c, cs], xr, "xr", ["rwkv_wr"])
            P.cp("scalar", t["r"][:], pp[i], [f"pp{i}"], ["r"])
            i = proj8(lambda c: wk[:, c, cs], xk, "xk", ["rwkv_wk"])
            P.cp("scalar", t["k"][:], pp[i], [f"pp{i}"], ["k"])
            i = proj8(lambda c: wv[:, c, cs], xv, "xv", ["rwkv_wv"])
            vt4 = VTbd[:].rearrange("p u (h s) -> p u h s", h=2)
            for h2 in range(2):
                sl = slice(h2 * 64, (h2 + 1) * 64)
                P.cp("scalar", vt4[sl, :, h2, :], v3(pp[i][sl, :]), [f"pp{i}"], ["VTbd"])
            i = nxt("pp")
            P.mm(pp[i], g2[:, cs], sg[:], True, True, ["g2", "sg"], [f"pp{i}"])
            gt4 = GTbd[:].rearrange("p u (h s) -> p u h s", h=2)
            for h2 in range(2):
                sl = slice(h2 * 64, (h2 + 1) * 64)
                P.cp("scalar", gt4[sl, :, h2, :], v3(pp[i][sl, :]), [f"pp{i}"], ["GTbd"])
            yield
            j = nxt("pb")
            for u in range(4):
                P.tr(pb[j][:, u * 128:(u + 1) * 128], VTbd[:, u, :], identf[:], ["VTbd", "identf"], [f"pb{j}"])
            pv = u128(pb[j][:])
            for h2 in range(2):
                sl = slice(h2 * 64, (h2 + 1) * 64)
                P.cp("scalar", Vf[sl, :, :], pv[sl, :, h2 * 64:(h2 + 1) * 64], [f"pb{j}"], ["Vf"])
            P.cp("gpsimd", vb_st[:], Vf[:], ["Vf"], ["vb_st"])
            P.dma("sync", S["vst"][tg, oc].rearrange("p (u s) -> p u s", s=64), Vf[:], reads=["Vf"], writes=[("vst", tg, oc)], sem="Vf")
            j = nxt("pb")
            for u in range(4):
                P.tr(pb[j][:, u * 128:(u + 1) * 128], GTbd[:, u, :], identf[:], ["GTbd", "identf"], [f"pb{j}"])
            pv = u128(pb[j][:])
            for h2 in range(2):
                sl = slice(h2 * 64, (h2 + 1) * 64)
                P.cp("scalar", Gf[sl, :, :], pv[sl, :, h2 * 64:(h2 + 1) * 64], [f"pb{j}"], ["Gf"])
            P.dma("sync", S["gst"][tg, oc].rearrange("p (u s) -> p u s", s=64), Gf[:], reads=["Gf"], writes=[("gst", tg, oc)], sem="Gf")
            yield
            for d in range(2):
                dl = slice(d * 64, (d + 1) * 64)
                i = nxt("pp")
                P.mm(pp[i], w2s[dl, cs], lwt[dl, :], True, True, ["w2s", "lwt"], [f"pp{i}"])
                P.act(t[f"sw{d}"][:], pp[i], AF.Sigmoid, [f"pp{i}", "vec"], [f"sw{d}"], bias=vec[:, 11 + d, oc:oc + 1])
                i = nxt("pp")
                P.mm(pp[i], a2s[dl, cs], lat[dl, :], True, True, ["a2s", "lat"], [f"pp{i}"])
                P.act(t[f"ag{d}"][:], pp[i], AF.Sigmoid, [f"pp{i}", "vec"], [f"ag{d}"], bias=vec[:, 13 + d, oc:oc + 1])
            yield
            P.ts("vector", t["kq"][:], t["k"][:], vec[:, 15, oc:oc + 1], None, ALU.mult, None, ["k", "vec"], ["kq"])
            P.act(sqb[:], t["kq"][:], AF.Square, ["kq"], ["sqb"])
            i = nxt("pp")
            P.mm(pp[i], bones[:], sqb[:], True, True, ["bones", "sqb"], [f"pp{i}"])
            P.act(t["lnv"][:], pp[i], AF.Ln, [f"pp{i}"], ["lnv"], bias=1e-12)
            P.act(t["rs"][:], t["lnv"][:], AF.Exp, ["lnv"], ["rs"], scale=-0.5)
            P.tt("vector", t["kkn"][:], t["kq"][:], t["rs"][:], ALU.mult, ["kq", "rs"], ["kkn"])
            for d in range(2):
                sw, ag, kd, bb = t[f"sw{d}"], t[f"ag{d}"], t[f"kd{d}"], t[f"b{d}"]
                EE = "vector"
                P.ts(EE, t["fac"][:], ag[:], vec[:, 16, oc:oc + 1], vec[:, 17, oc:oc + 1], ALU.mult, ALU.add, [f"ag{d}", "vec"], ["fac"])
                P.tt(EE, kd[:], t["k"][:], t["fac"][:], ALU.mult, ["k", "fac"], [f"kd{d}"])
                P.tt(EE, bb[:], t["kkn"][:], ag[:], ALU.mult, ["kkn", f"ag{d}"], [f"b{d}"])
                P.op("vector", lambda e, sw=sw: e.tensor_tensor_scan(out=t["L"][:], data0=rmask[:], data1=sw[:], initial=0.0,
                                                                      op0=ALU.mult, op1=ALU.add), [f"sw{d}", "rmask"], ["L"])
                L3 = v3(t["L"][:])
                if d == 0:
                    P.tt(EE, t["Lx"][:], t["L"][:], sw[:], ALU.subtract, ["L", f"sw{d}"], ["Lx"])
                    Li, Lin = t["L"], "L"
                else:
                    P.tt(EE, v3(t["Lx"][:]), L3[:, :, 63:64].broadcast_to([128, 4, 64]), L3, ALU.subtract, ["L"], ["Lx"])
                    P.tt(EE, t["Lb"][:], t["Lx"][:], sw[:], ALU.add, ["Lx", f"sw{d}"], ["Lb"])
                    Li, Lin = t["Lb"], "Lb"
                P.act(t["E1"][:], Li[:], AF.Exp, [Lin], ["E1"], scale=-C0)
                P.act(t["E3"][:], Li[:], AF.Exp, [Lin], ["E3"], scale=C0)
                P.act(t["E2"][:], t["Lx"][:], AF.Exp, ["Lx"], ["E2"], scale=-C0)
                P.stt(ops_st[:, d, 0, :], t["kkn"][:], -1.0, t["E2"][:], ALU.mult, ALU.mult, ["kkn", "E2"], ["ops_st"])
                P.tt("vector", ops_st[:, d, 1, :], t["r"][:], t["E1"][:], ALU.mult, ["r", "E1"], ["ops_st"])
                P.tt(EE, ops_st[:, d, 2, :], kd[:], t["E3"][:], ALU.mult, [f"kd{d}", "E3"], ["ops_st"])
                P.tt(EE, ops_st[:, d, 3, :], bb[:], t["E3"][:], ALU.mult, [f"b{d}", "E3"], ["ops_st"])
                E13 = v3(t["E1"][:])
                gsrc = E13[:, :, 63] if d == 0 else E13[:, :, 0]
                P.cp("vector", gam_st[:, d, :], gsrc, ["E1"], ["gam_st"])
                if d == 1:
                    P.cp("gpsimd", gamb_t[:, oc, :], gam_st[:, 1, :], ["gam_st"], ["gamb_t"])
                yield
            P.tt("vector", t["ks"][:], t["kd0"][:], t["kd1"][:], ALU.add, ["kd0", "kd1"], ["ks"])
            for h2 in range(2):
                sl = slice(h2 * 64, (h2 + 1) * 64)
                P.stt(RK[sl, :, h2, :], v3(t["r"][sl, :]), vec[sl, 18, oc:oc + 1], v3(t["ks"][sl, :]), ALU.mult, ALU.mult, ["r", "ks", "vec"], ["RK"])
            i = nxt("pp")
            for u in range(4):
                P.mm(pp[i][:, u:u + 1], RK[:, u, :, :].rearrange("p h s -> p (h s)"), onesb[:, 0:1], True, True, ["RK", "onesb"], [f"pp{i}"])
            P.cp("scalar", bon_t[:, oc, :], pp[i][:, 0:4], [f"pp{i}"], ["bon_t"])
            P.dma("sync", S["ops"][tg, oc], ops_st[:].rearrange("p d x n -> p (d x n)"), reads=["ops_st"], writes=[("ops", tg, oc)], sem="ops_st")
            P.dma("sync", S["vb"][tg, oc], vb_st[:].rearrange("p u s -> p (u s)"), reads=["vb_st"], writes=[("vb", tg, oc)], sem="vb_st")
            P.dma("sync", S["gam"][tg, oc], gam_st[:].rearrange("p d u -> p (d u)"), reads=["gam_st"], writes=[("gam", tg, oc)], sem="gam_st")
            yield


        NT = len(RW_ORDER1)
        load_hh(0)
        for ti in range(NT):
            isctx, idx = RW_ORDER1[ti]
            tg = 16 if isctx else idx
            for _ in tprep(ti):
                pass
            jobs = [(oc % NSET, prep(ti, oc, oc % NSET)) for oc in range(8)]
            active = []
            since = 99
            while jobs or active:
                if jobs and len(active) < NSET and (since >= 3 or not active):
                    active.append(jobs.pop(0))
                    since = 0
                since += 1
                for item in list(active):
                    P.ns = item[0]
                    try:
                        next(item[1])
                    except StopIteration:
                        active.remove(item)
                    P.ns = None
            P.dma("sync", S["gamb"][tg], gamb_t[:].rearrange("p a b -> p (a b)"), reads=["gamb_t"], writes=[("gamb", tg)], sem="gamb_t")
            P.dma("sync", S["bon"][tg], bon_t[:].rearrange("p a b -> p (a b)"), reads=["bon_t"], writes=[("bon", tg)], sem="bon_t")
        P.ns_set = frozenset()


def stage_rwkv1b(P, io, G, S):
    vec, masks, identb, identf, bones, onesb, rmask = (G[k] for k in ("vec", "masks", "identb", "identf", "bones", "onesb", "rmask"))
    with P.phase("rwkv1b"):
        YPs = P.sb([128, 4, 64], F32)
        SAs = P.sb([128, 4, 64], F32)
        Sf = P.sb([128, 8, 64], BF16)
        ARq = [[P.sb([128, 4, 2, 128], BF16, f"AR{q}{d}") for d in range(2)] for q in range(3)]
        KTq = [[P.sb([128, 4, 128], BF16, f"KT{q}{d}") for d in range(2)] for q in range(3)]
        BTq = [[P.sb([128, 4, 128], BF16, f"BT{q}{d}") for d in range(2)] for q in range(3)]
        stg = [P.sb([128, 2, 4, 256], BF16, f"stg{q}") for q in range(3)]
        Vbq = [P.sb([128, 4, 64], BF16, f"Vb{q}") for q in range(4)]
        gamq = [P.sb([128, 2, 4], F32, f"gam{q}") for q in range(4)]
        inv2 = []
        for q in range(2):
            row = []
            for d in range(2):
                st = {}
                for nm, shp in (("Atok", [128, 4, 128]), ("Btok", [128, 4, 128]), ("MQ", [128, 4, 256]), ("MWa", [128, 4, 2, 128]),
                                ("MWb", [128, 4, 2, 128]), ("MTa", [128, 4, 128]), ("MTb", [128, 4, 128])):
                    st[nm] = P.sb(shp, BF16, f"i{q}{d}_{nm}")
                row.append(st)
            inv2.append(row)
        fin = []
        for q in range(2):
            row = []
            for d in range(2):
                st = {}
                for nm, shp in (("Ktok", [128, 4, 128]), ("NP", [128, 4, 256]), ("XW", [128, 4, 256]), ("NVb", [128, 4, 64]),
                                ("GY", [128, 4, 128]), ("GS", [128, 4, 128])):
                    st[nm] = P.sb(shp, BF16, f"f{q}{d}_{nm}")
                row.append(st)
            fin.append(row)
        pf = P.ps([128, 512], F32)
        pb = [P.ps([128, 512], F32) for _ in range(7)]
        cnt = {"pb": 0}
        nmod = {"pb": 7}

        def nxt(kind):
            i = cnt[kind] % nmod[kind]
            cnt[kind] += 1
            return i

        for q in range(3):
            for d in range(2):
                P.memset("gpsimd", ARq[q][d][:], 0.0, [f"AR{q}{d}"])
                P.memset("gpsimd", KTq[q][d][:], 0.0, [f"KT{q}{d}"])
                P.memset("gpsimd", BTq[q][d][:], 0.0, [f"BT{q}{d}"])
        P.memset("gpsimd", Sf[:], 0.0, [("Sf", p) for p in range(8)])

        def v3(ap):
            return ap.rearrange("p (u s) -> p u s", s=64)

        def u128(ap):
            return ap.rearrange("p (u x) -> p u x", x=128)

        def loadjob(ti, oc, a, z):
            isctx, idx = RW_ORDER1[ti]
            tg = 16 if isctx else idx
            sg_ = stg[a]
            P.dma("sync", sg_[:].rearrange("p d x n -> p (d x n)"), S["ops"][tg, oc], writes=[f"stg{a}"], sem=f"stg{a}")
            P.dma("sync", Vbq[z][:].rearrange("p u s -> p (u s)"), S["vb"][tg, oc], writes=[f"Vb{z}"], sem=f"Vb{z}")
            P.dma("sync", gamq[z][:].rearrange("p d u -> p (d u)"), S["gam"][tg, oc], writes=[f"gam{z}"], sem=f"gam{z}")
            yield
            for d in range(2):
                ar5 = ARq[a][d][:].rearrange("p u a (h s) -> p u a h s", h=2)
                kt4 = KTq[a][d][:].rearrange("p u (h s) -> p u h s", h=2)
                bt4 = BTq[a][d][:].rearrange("p u (h s) -> p u h s", h=2)
                for h2 in range(2):
                    sl = slice(h2 * 64, (h2 + 1) * 64)
                    P.cp("gpsimd", ar5[sl, :, 0, h2, :], v3(sg_[sl, d, 0, :]), [f"stg{a}"], [f"AR{a}{d}"])
                    P.cp("gpsimd", ar5[sl, :, 1, h2, :], v3(sg_[sl, d, 1, :]), [f"stg{a}"], [f"AR{a}{d}"])
                    P.cp("gpsimd", kt4[sl, :, h2, :], v3(sg_[sl, d, 2, :]), [f"stg{a}"], [f"KT{a}{d}"])
                    P.cp("gpsimd", bt4[sl, :, h2, :], v3(sg_[sl, d, 3, :]), [f"stg{a}"], [f"BT{a}{d}"])
                    yield

        def chain(ti, oc, q, d, z, a):
            AR, KT, BT, Vb = ARq[a][d], KTq[a][d], BTq[a][d], Vbq[z]
            ARn, KTn, BTn, Vbn = f"AR{a}{d}", f"KT{a}{d}", f"BT{a}{d}", f"Vb{z}"
            iv, fn = inv2[q][d], fin[q][d]
            IR = lambda nm: f"i{q}{d}_{nm}"
            FR = lambda nm: f"f{q}{d}_{nm}"
            mS, mC = (0, 2) if d == 0 else (2, 0)
            mSI = masks[:, mS:mS + 2, :].rearrange("p a b -> p (a b)").unsqueeze(1).broadcast_to([128, 4, 256])
            mCb = masks[:, mC, :].unsqueeze(1).broadcast_to([128, 4, 128])
            idb = identb[:].unsqueeze(1).broadcast_to([128, 4, 128])
            for src, srcn, dst, dstn in ((AR[:, :, 0, :], ARn, iv["Atok"], IR("Atok")), (BT[:], BTn, iv["Btok"], IR("Btok")),
                                         (KT[:], KTn, fn["Ktok"], FR("Ktok"))):
                j = nxt("pb")
                pbt = pb[j][:].bitcast(BF16)
                for u in range(4):
                    P.tr(pbt[:, u * 128:(u + 1) * 128], src[:, u, :], identb[:], [srcn, "identb"], [f"pb{j}"])
                P.cp("scalar", dst[:].rearrange("p u x -> p (u x)"), pbt[:, 0:512], [f"pb{j}"], [dstn])
            mSb = masks[:, mS, :].unsqueeze(1).broadcast_to([128, 4, 128])
            mIb = masks[:, mS + 1, :].unsqueeze(1).broadcast_to([128, 4, 128])

            def two_bank(mm_fn):
                j0, j1 = nxt("pb"), nxt("pb")
                for u in range(4):
                    mm_fn(u, pb[j0][:, u * 128:(u + 1) * 128], f"pb{j0}", pb[j1][:, u * 128:(u + 1) * 128], f"pb{j1}")
                return j0, j1

            for lhs, lhsn, dst, dstn in ((BT, BTn, iv["MQ"], IR("MQ")), (KT, KTn, fn["NP"], FR("NP"))):
                def mm_ab(u, o0, n0, o1, n1, lhs=lhs, lhsn=lhsn):
                    P.mm(o0, lhs[:, u, :], AR[:, u, 0, :], True, True, [lhsn, ARn], [n0])
                    P.mm(o1, lhs[:, u, :], AR[:, u, 1, :], True, True, [lhsn, ARn], [n1])
                j0, j1 = two_bank(mm_ab)
                P.tt("vector", dst[:, :, 0:128], u128(pb[j0][:]), mSb, ALU.mult, [f"pb{j0}", "masks"], [dstn])
                P.tt("vector", dst[:, :, 128:256], u128(pb[j1][:]), mIb, ALU.mult, [f"pb{j1}", "masks"], [dstn])
            j = nxt("pb")
            for u in range(4):
                P.mm(pb[j][:, u * 128:(u + 1) * 128], AR[:, u, 0, :], BT[:, u, :], True, True, [ARn, BTn], [f"pb{j}"])
            cur, curn, nx, nxn = iv["MWa"], IR("MWa"), iv["MWb"], IR("MWb")
            P.tt("vector", cur[:, :, 0, :], u128(pb[j][:]), mCb, ALU.mult, [f"pb{j}", "masks"], [curn])
            yield
            j = nxt("pb")
            for u in range(4):
                P.mm(pb[j][:, u * 128:(u + 1) * 128], iv["MQ"][:, u, 0:128], cur[:, u, 0, :], True, True, [IR("MQ"), curn], [f"pb{j}"])
            P.cp("scalar", nx[:, :, 0, :], u128(pb[j][:]), [f"pb{j}"], [nxn])
            P.tt("gpsimd", nx[:, :, 1, :], cur[:, :, 0, :], idb, ALU.add, [curn, "identb"], [nxn])
            j = nxt("pb")
            for u in range(4):
                P.mm(pb[j][:, u * 128:(u + 1) * 128], cur[:, u, 0, :], iv["MQ"][:, u, 0:128], True, True, [IR("MQ"), curn], [f"pb{j}"])
            curT, curTn, nxT, nxTn = iv["MTa"], IR("MTa"), iv["MTb"], IR("MTb")
            P.cp("scalar", curT[:], u128(pb[j][:]), [f"pb{j}"], [curTn])
            cur, curn, nx, nxn = nx, nxn, cur, curn
            yield
            for lev in range(1, 5):
                def mm_lev(u, o0, n0, o1, n1, cur=cur, curn=curn, curT=curT, curTn=curTn):
                    P.mm(o0, curT[:, u, :], cur[:, u, 0, :], True, True, [curTn, curn], [n0])
                    P.mm(o1, curT[:, u, :], cur[:, u, 1, :], True, True, [curTn, curn], [n1])
                j0, j1 = two_bank(mm_lev)
                P.cp("scalar", nx[:, :, 0, :], u128(pb[j0][:]), [f"pb{j0}"], [nxn])
                P.tt("vector", nx[:, :, 1, :], u128(pb[j1][:]), cur[:, :, 1, :], ALU.add, [f"pb{j1}", curn], [nxn])
                j = nxt("pb")
                for u in range(4):
                    P.mm(pb[j][:, u * 128:(u + 1) * 128], cur[:, u, 0, :], curT[:, u, :], True, True, [curn, curTn], [f"pb{j}"])
                P.cp("scalar", nxT[:], u128(pb[j][:]), [f"pb{j}"], [nxTn])
                cur, curn, nx, nxn = nx, nxn, cur, curn
                curT, curTn, nxT, nxTn = nxT, nxTn, curT, curTn
                yield
            j = nxt("pb")
            for u in range(4):
                P.mm(pb[j][:, u * 128:(u + 1) * 128], curT[:, u, :], cur[:, u, 1, :], True, True, [curTn, curn], [f"pb{j}"])
            P.tt("vector", nx[:, :, 1, :], u128(pb[j][:]), cur[:, :, 1, :], ALU.add, [f"pb{j}", curn], [nxn])
            W6, W6n = nx, nxn
            j = nxt("pb")
            for u in range(4):
                P.mm(pb[j][:, u * 64:(u + 1) * 64], fn["NP"][:, u, 0:128], Vb[:, u, :], True, True, [FR("NP"), Vbn], [f"pb{j}"])
            P.cp("scalar", fn["NVb"][:].rearrange("p u x -> p (u x)"), pb[j][:, 0:256], [f"pb{j}"], [FR("NVb")])
            yield

            def mm_d(u, o0, n0, o1, n1):
                P.mm(o0, W6[:, u, 1, :], iv["MQ"][:, u, 128:256], True, True, [W6n, IR("MQ")], [n0])
                P.mm(o1, W6[:, u, 1, :], iv["Btok"][:, u, :], True, True, [W6n, IR("Btok")], [n1])
            j0, j1 = two_bank(mm_d)
            P.cp("scalar", fn["XW"][:, :, 0:128], u128(pb[j0][:]), [f"pb{j0}"], [FR("XW")])
            P.cp("vector", fn["XW"][:, :, 128:256], u128(pb[j1][:]), [f"pb{j1}"], [FR("XW")])
            yield

            def mm_f(u, o0, n0, o1, n1):
                P.mm(o0, iv["Atok"][:, u, :], fn["XW"][:, u, 0:128], True, True, [IR("Atok"), FR("XW")], [n0])
                P.mm(o1, iv["Atok"][:, u, :], fn["XW"][:, u, 128:256], True, True, [IR("Atok"), FR("XW")], [n1])
            j0, j1 = two_bank(mm_f)
            P.tt("vector", fn["GY"][:], u128(pb[j0][:]), AR[:, :, 1, :], ALU.add, [f"pb{j0}", ARn], [FR("GY")])
            P.tt("vector", fn["GS"][:], u128(pb[j1][:]), idb, ALU.add, [f"pb{j1}", "identb"], [FR("GS")])
            yield

        def finish(ti, oc, q, z):
            isctx, idx = RW_ORDER1[ti]
            tg = 16 if isctx else idx
            sf, sb_ = fin[q]
            F0 = lambda nm: f"f{q}0_{nm}"
            F1 = lambda nm: f"f{q}1_{nm}"
            Vb, Vbn, gamz = Vbq[z], f"Vb{z}", gamq[z]
            SFR = ("Sf", oc)
            for u in range(4):
                yo = pf[:, u * 64:(u + 1) * 64]
                P.mm(yo, sf["NP"][:, u, 128:256], Vb[:, u, :], True, False, [F0("NP"), Vbn], ["pf"])
                P.mm(yo, sf["XW"][:, u, 0:128], sf["NVb"][:, u, :], False, False, [F0("XW"), F0("NVb")], ["pf"])
                P.mm(yo, sb_["NP"][:, u, 128:256], Vb[:, u, :], False, False, [F1("NP"), Vbn], ["pf"])
                P.mm(yo, sb_["XW"][:, u, 0:128], sb_["NVb"][:, u, :], False, False, [F1("XW"), F1("NVb")], ["pf"])
                P.mm(yo, sf["GY"][:, u, :], Sf[:, oc, :], False, True, [F0("GY"), SFR], ["pf"])
                so = pf[:, 256:320]
                P.mm(so, sf["Ktok"][:, u, :], Vb[:, u, :], True, False, [F0("Ktok"), Vbn], ["pf"])
                P.mm(so, sf["XW"][:, u, 128:256], sf["NVb"][:, u, :], False, False, [F0("XW"), F0("NVb")], ["pf"])
                P.mm(so, sf["GS"][:, u, :], Sf[:, oc, :], False, True, [F0("GS"), SFR], ["pf"])
                P.ts("vector", Sf[:, oc, :], so, gamz[:, 0, u:u + 1], None, ALU.mult, None, ["pf", f"gam{z}"], [SFR])
                yield
            P.cp("vector", YPs[:].rearrange("p u x -> p (u x)"), pf[:, 0:256], ["pf"], ["YPs"])
            P.dma("sync", S["yp"][tg, oc], YPs[:].rearrange("p u x -> p (u x)"), reads=["YPs"], writes=[("yp", tg, oc)], sem="YPs")
            j = nxt("pb")
            for u in range(4):
                so = pb[j][:, u * 64:(u + 1) * 64]
                P.mm(so, sb_["Ktok"][:, u, :], Vb[:, u, :], True, False, [F1("Ktok"), Vbn], [f"pb{j}"])
                P.mm(so, sb_["XW"][:, u, 128:256], sb_["NVb"][:, u, :], False, True, [F1("XW"), F1("NVb")], [f"pb{j}"])
            P.cp("scalar", SAs[:].rearrange("p u x -> p (u x)"), pb[j][:, 0:256], [f"pb{j}"], ["SAs"])
            P.dma("sync", S["sadd"][tg, oc], SAs[:].rearrange("p u x -> p (u x)"), reads=["SAs"], writes=[("sadd", tg, oc)], sem="SAs")
            P.dma("sync", S["gyb"][tg, oc], sb_["GY"][:].rearrange("p u x -> p (u x)"), reads=[F1("GY")], writes=[("gyb", tg, oc)], sem=F1("GY"))
            P.dma("sync", S["gsb"][tg, oc], sb_["GS"][:].rearrange("p u x -> p (u x)"), reads=[F1("GS")], writes=[("gsb", tg, oc)], sem=F1("GS"))
            yield


        NT = len(RW_ORDER1)
        NJ = NT * 8
        donef = set()

        def stream_L():
            for k in range(NJ):
                ti, oc = divmod(k, 8)
                yield ("load", k, lambda k=k: ((k < 3 or (("c0", k - 3) in donef and ("c1", k - 3) in donef)) and (k < 4 or ("fin", k - 4) in donef)),
                       lambda ti=ti, oc=oc, k=k: loadjob(ti, oc, k % 3, k % 4))

        def stream_C(d, par):
            for k in range(par, NJ, 2):
                ti, oc = divmod(k, 8)
                yield (f"c{d}", k, lambda k=k: (("load", k) in donef and (k < 2 or ("fin", k - 2) in donef)),
                       lambda ti=ti, oc=oc, k=k: chain(ti, oc, k % 2, d, k % 4, k % 3))

        def stream_F():
            for k in range(NJ):
                ti, oc = divmod(k, 8)
                yield ("fin", k, lambda k=k: (("c0", k) in donef and ("c1", k) in donef),
                       lambda ti=ti, oc=oc, k=k: finish(ti, oc, k % 2, k % 4))

        streams = [stream_L(), stream_C(0, 0), stream_C(1, 0), stream_C(0, 1), stream_C(1, 1), stream_F()]
        NS_ = len(streams)
        cur = [None] * NS_
        pend = [None] * NS_
        alive = [True] * NS_
        while any(alive):
            progressed = False
            for si in range(NS_):
                if not alive[si]:
                    continue
                if cur[si] is None:
                    if pend[si] is None:
                        try:
                            pend[si] = next(streams[si])
                        except StopIteration:
                            alive[si] = False
                            continue
                    kind, k, ready, mk = pend[si]
                    if not ready():
                        continue
                    cur[si] = (kind, k, mk())
                    pend[si] = None
                kind, k, gen = cur[si]
                try:
                    next(gen)
                    progressed = True
                except StopIteration:
                    donef.add((kind, k))
                    cur[si] = None
                    progressed = True
            assert progressed or not any(alive), "scheduler stuck"


def stage_rwkv2(P, io, G, S, src, xa):
    vec, identb = G["vec"], G["identb"]
    GN_EPS = 64e-5
    with P.phase("rwkv2"):
        wo = P.sb([64, 16, 1024], BF16)
        P.dma("gpsimd", wo[:], io["rwkv_wo"].rearrange("(h v) f -> v h f", v=64), writes=["wo"], sem="wo")
        lnw = P.sb([128, 8, 64], F32)
        lnb = P.sb([128, 8, 64], F32)
        P.dma("sync", lnw[:], io["lnw_st"], writes=["lnw"], sem="lnw")
        P.dma("sync", lnb[:], io["lnb_st"], writes=["lnb"], sem="lnb")
        big = {}
        for nm in ("yp", "sadd", "vst", "gst"):
            big[nm] = [P.sb([128, 8, 256], F32, f"l_{nm}{b}") for b in range(2)]
        for nm in ("gyb", "gsb"):
            big[nm] = [P.sb([128, 8, 512], BF16, f"l_{nm}{b}") for b in range(2)]
        gamb = [P.sb([128, 8, 4], F32) for _ in range(2)]
        bon = [P.sb([128, 8, 4], F32) for _ in range(2)]
        xt = [P.sb([128, 8, 256], F32) for _ in range(2)]
        Sb = P.sb([128, 8, 64], BF16)
        ysb2 = [P.sb([128, 8, 64], F32) for _ in range(2)]
        ysq2 = [P.sb([128, 8, 64], F32) for _ in range(2)]
        tmpS = P.sb([128, 8, 64], F32)
        yn2 = [P.sb([128, 8, 64], F32) for _ in range(2)]
        bv2 = [P.sb([128, 8, 64], F32) for _ in range(2)]
        ob2 = [P.sb([128, 8, 64], BF16) for _ in range(2)]
        st2 = [{nm: P.sb([128, 8], F32, f"g{k_}_" + nm) for nm in ("s1", "s2", "mean", "msq", "var", "lnv", "rstd")} for k_ in range(2)]
        OT = P.sb([64, 16, 256], BF16)
        py = [P.ps([128, 512], F32) for _ in range(2)]
        pS = P.ps([128, 512], F32)
        ptr = P.ps([128, 1024], F32)
        pw = [P.ps([128, 512], F32) for _ in range(2)]
        P.memset("gpsimd", Sb[:], 0.0, ["Sb"])

        def load(k):
            isctx, idx = RW_ORDER2[k]
            tg = 16 if isctx else idx
            b = k % 2
            for nm in ("yp", "sadd", "vst", "gst", "gyb", "gsb"):
                P.dma("sync", big[nm][b][:], S[nm][tg].rearrange("o p x -> p o x"), writes=[f"{nm}{b}"], sem=f"{nm}{b}")
            P.dma("sync", gamb[b][:].rearrange("p a b -> p (a b)"), S["gamb"][tg], writes=[f"gamb{b}"], sem=f"gamb{b}")
            P.dma("sync", bon[b][:].rearrange("p a b -> p (a b)"), S["bon"][tg], writes=[f"bon{b}"], sem=f"bon{b}")
            c0 = T if isctx else idx * 256
            P.dma("sync", xt[b][:], fm(src[:, c0:c0 + 256]), writes=[f"xt{b}"], sem=f"xt{b}")

        load(0)
        for k, (isctx, idx) in enumerate(RW_ORDER2):
            b = k % 2
            if k + 1 < len(RW_ORDER2):
                load(k + 1)
            c0 = T if isctx else idx * 256
            _, _, gates = mod_scalars(G, 0, 0, isctx)
            bc = lambda ap: ap.unsqueeze(2).broadcast_to([128, 8, 64])
            def chain_part(u):
                us = slice(u * 64, (u + 1) * 64)
                q_ = u % 2
                for oc in range(8):
                    P.mm(py[q_][:, oc * 64:(oc + 1) * 64], big["gyb"][b][:, oc, u * 128:(u + 1) * 128], Sb[:, oc, :], True, True, [f"gyb{b}", "Sb"], [f"py{q_}"])
                for oc in range(8):
                    P.mm(pS[:, oc * 64:(oc + 1) * 64], big["gsb"][b][:, oc, u * 128:(u + 1) * 128], Sb[:, oc, :], True, True, [f"gsb{b}", "Sb"], ["pS"])
                pS3 = pS[:].rearrange("p (o v) -> p o v", v=64)
                P.tt("vector", tmpS[:], pS3, big["sadd"][b][:, :, us], ALU.add, ["pS", f"sadd{b}"], ["tmpS"])
                P.tt("vector", Sb[:], tmpS[:], bc(gamb[b][:, :, u]), ALU.mult, ["tmpS", f"gamb{b}"], ["Sb"])

            def read_part(u):
                us = slice(u * 64, (u + 1) * 64)
                q_ = u % 2
                ysb, ysq, yn, bv, ob, st = ysb2[q_], ysq2[q_], yn2[q_], bv2[q_], ob2[q_], st2[q_]
                N = lambda nm: f"{nm}{q_}"
                py3 = py[q_][:].rearrange("p (o v) -> p o v", v=64)
                P.tt("vector", ysb[:], py3, big["yp"][b][:, :, us], ALU.add, [f"py{q_}", f"yp{b}"], [N("ysb")])
                P.tt("gpsimd", bv[:], big["vst"][b][:, :, us], bc(bon[b][:, :, u]), ALU.mult, [f"vst{b}", f"bon{b}"], [N("bv")])
                yield
                P.op("vector", lambda e: e.tensor_reduce(out=st["s1"][:], in_=ysb[:], axis=AX.X, op=ALU.add), [N("ysb")], [N("s1")])
                P.tt("gpsimd", ysq[:], ysb[:], ysb[:], ALU.mult, [N("ysb")], [N("ysq")])
                yield
                P.op("vector", lambda e: e.tensor_reduce(out=st["s2"][:], in_=ysq[:], axis=AX.X, op=ALU.add), [N("ysq")], [N("s2")])
                P.ts("vector", st["mean"][:], st["s1"][:], 1.0 / 64, None, ALU.mult, None, [N("s1")], [N("mean")])
                P.tt("vector", st["msq"][:], st["mean"][:], st["mean"][:], ALU.mult, [N("mean")], [N("msq")])
                P.stt(st["var"][:], st["s2"][:], 1.0 / 64, st["msq"][:], ALU.mult, ALU.subtract, [N("s2"), N("msq")], [N("var")])
                yield
                P.act(st["lnv"][:], st["var"][:], AF.Ln, [N("var")], [N("lnv")], bias=GN_EPS)
                P.act(st["rstd"][:], st["lnv"][:], AF.Exp, [N("lnv")], [N("rstd")], scale=-0.5)
                P.tt("gpsimd", yn[:], ysb[:], bc(st["mean"][:]), ALU.subtract, [N("ysb"), N("mean")], [N("yn")])
                yield
                P.tt("vector", yn[:], yn[:], bc(st["rstd"][:]), ALU.mult, [N("yn"), N("rstd")], [N("yn")])
                yield
                P.tt("gpsimd", yn[:], yn[:], lnw[:], ALU.mult, [N("yn"), "lnw"], [N("yn")])
                yield
                P.tt("vector", yn[:], yn[:], lnb[:], ALU.add, [N("yn"), "lnb"], [N("yn")])
                yield
                P.tt("gpsimd", yn[:], yn[:], bv[:], ALU.add, [N("yn"), N("bv")], [N("yn")])
                yield
                P.tt("vector", ob[:], yn[:], big["gst"][b][:, :, us], ALU.mult, [N("yn"), f"gst{b}"], [N("ob")])
                yield
                ptb = ptr[:].bitcast(BF16)
                for oc in range(8):
                    P.tr(ptb[0:64, oc * 128:(oc + 1) * 128], ob[:, oc, :], identb[:], [N("ob"), "identb"], ["ptr"])
                P.cp("scalar", OT[:, :, us], ptb[0:64, 0:1024].rearrange("p (h t) -> p h t", t=64), ["ptr"], ["OT"])
                yield

            def chain_all():
                for u in range(3, -1, -1):
                    chain_part(u)
                    yield

            jobs = [read_part(u) for u in range(3, -1, -1)]
            cgen = chain_all()
            next(cgen)
            active = []
            started = 0
            while jobs or active:
                while jobs and len(active) < 2:
                    if started >= 1:
                        try:
                            next(cgen)
                        except StopIteration:
                            pass
                    active.append(jobs.pop(0))
                    started += 1
                for gen in list(active):
                    try:
                        next(gen)
                    except StopIteration:
                        active.remove(gen)
            for oc in range(8):
                j = oc % 2
                for h in range(16):
                    P.mm(pw[j][:, 0:256], wo[:, h, oc * 128:(oc + 1) * 128], OT[:, h, :], h == 0, h == 15, ["wo", "OT"], [f"pw{j}"])
                P.stt(xt[b][:, oc, :], pw[j][:, 0:256], gates[oc], xt[b][:, oc, :], ALU.mult, ALU.add, [f"pw{j}", f"xt{b}", "modv"], [f"xt{b}"])
            P.dma("sync", fm(xa[:, c0:c0 + 256]), xt[b][:], reads=[f"xt{b}"], writes=[("xa", k)], sem=f"xt{b}")


def stage_qkv(P, io, G, hb, qtd, Kz, VA):
    vec, bones, perm = G["vec"], G["bones"], G["perm"]
    with P.phase("qkv"):
        wq = P.sb([128, 8, 1024], BF16)
        wkd = P.sb([128, 8, 512], BF16)
        wv = P.sb([128, 8, 256], BF16)
        P.dma("gpsimd", wq[:], fm(io["attn_wq"]), writes=["wq"], sem="wq")
        P.dma("gpsimd", wkd[:], fm(io["attn_wkd"]), writes=["wkd"], sem="wkd")
        P.dma("gpsimd", wv[:], fm(io["attn_wv"]), writes=["wv"], sem="wv")
        ht = [P.sb([128, 8, 512], BF16) for _ in range(2)]
        cs = [P.sb([128, 512], F32) for _ in range(2)]
        sn = [P.sb([128, 512], F32) for _ in range(2)]
        NB = 2
        qf = [P.sb([128, 512], F32) for _ in range(NB)]
        sqb = [P.sb([128, 512], BF16) for _ in range(NB)]
        lnv = [P.sb([128, 512], F32) for _ in range(NB)]
        rstd = [P.sb([128, 512], F32) for _ in range(NB)]
        qh = [P.sb([128, 512], F32) for _ in range(NB)]
        qhb = [P.sb([128, 512], BF16) for _ in range(NB)]
        t1 = [P.sb([128, 512], F32) for _ in range(NB)]
        t2 = [P.sb([128, 512], F32) for _ in range(NB)]
        qst = [P.sb([128, 8, 512], BF16) for _ in range(2)]
        pp = [P.ps([128, 512], F32) for _ in range(6)]
        cnt = [0, 0]

        def nxt():
            cnt[0] += 1
            return cnt[0] % 6

        P.memset("gpsimd", VA[:], 0.0, ["VA0"])
        P.memset("gpsimd", VA[:].rearrange("p k (j x) -> p k j x", x=65)[:, :, 0:5, 64:65], 1.0, ["VA0"])
        P.memset("gpsimd", Kz[0][64:128, :, :], 0.0, ["Kz0z"])
        P.memset("gpsimd", Kz[1][0:64, :, :], 0.0, ["Kz1z"])
        tiles = ALL_TILES

        def load(i):
            c0, tw, isctx = tiles[i]
            b = i % 2
            P.dma("sync", ht[b][:, :, :tw], fm(hb[:, c0:c0 + tw]), writes=[f"ht{b}"], sem=f"ht{b}")
            if not isctx:
                P.dma("sync", cs[b][:, :tw], io["cosT"][:, c0:c0 + tw], writes=[f"cs{b}"], sem=f"cs{b}")
                P.dma("sync", sn[b][:, :tw], io["sinT"][:, c0:c0 + tw], writes=[f"sn{b}"], sem=f"sn{b}")

        def normrope(wcols, nscal, dsts, b, tw, isctx, wname, dres="dstqk"):
            cnt[1] += 1
            n = cnt[1] % NB
            i = nxt()
            for c in range(8):
                P.mm(pp[i][:, :tw], wcols(c), ht[b][:, c, :tw], c == 0, c == 7, [wname, f"ht{b}"], [f"pp{i}"])
            P.cp("scalar", qf[n][:, :tw], pp[i][:, :tw], [f"pp{i}"], [f"qf{n}"])
            P.act(sqb[n][:, :tw], qf[n][:, :tw], AF.Square, [f"qf{n}"], [f"sqb{n}"])
            yield
            i = nxt()
            P.mm(pp[i][:, :tw], bones[:], sqb[n][:, :tw], True, True, ["bones", f"sqb{n}"], [f"pp{i}"])
            P.act(lnv[n][:, :tw], pp[i][:, :tw], AF.Ln, [f"pp{i}"], [f"lnv{n}"], bias=1e-6, scale=1.0 / 64)
            P.act(rstd[n][:, :tw], lnv[n][:, :tw], AF.Exp, [f"lnv{n}"], [f"rstd{n}"], scale=-0.5)
            yield
            P.stt(qh[n][:, :tw], qf[n][:, :tw], nscal, rstd[n][:, :tw], ALU.mult, ALU.mult, [f"qf{n}", f"rstd{n}", "vec"], [f"qh{n}"])
            if isctx:
                for dst, sl in dsts:
                    P.cp("vector", dst, qh[n][sl, :tw], [f"qh{n}"], [dres])
                return
            P.cp("vector", qhb[n][:, :tw], qh[n][:, :tw], [f"qh{n}"], [f"qhb{n}"])
            yield
            i = nxt()
            P.mm(pp[i][:, :tw], perm[:], qhb[n][:, :tw], True, True, ["perm", f"qhb{n}"], [f"pp{i}"])
            P.tt("vector", t1[n][:, :tw], qh[n][:, :tw], cs[b][:, :tw], ALU.mult, [f"qh{n}", f"cs{b}"], [f"t1{n}"])
            P.tt("vector", t2[n][:, :tw], pp[i][:, :tw], sn[b][:, :tw], ALU.mult, [f"pp{i}", f"sn{b}"], [f"t2{n}"])
            yield
            for dst, sl in dsts:
                P.tt("vector", dst, t1[n][sl, :tw], t2[n][sl, :tw], ALU.add, [f"t1{n}", f"t2{n}"], [dres])

        ALLP = slice(0, 128)
        load(0)
        for i, (c0, tw, isctx) in enumerate(tiles):
            b = i % 2
            if i + 1 < len(tiles):
                load(i + 1)
            jobs = []
            if not isctx:
                for oc in range(8):
                    jobs.append(normrope(lambda c, oc=oc: wq[:, c, oc * 128:(oc + 1) * 128], vec[:, 19, oc:oc + 1], [(qst[b][:, oc, :tw], ALLP)], b, tw, False, "wq",
                                         dres=(f"qst{b}", oc)))
            for g in range(4):
                jobs.append(normrope(lambda c, g=g: wkd[:, c, g * 128:(g + 1) * 128], vec[:, 20, 0:1],
                                     [(Kz[0][0:64, g, c0:c0 + tw], slice(0, 64)), (Kz[1][64:128, g, c0:c0 + tw], slice(64, 128))], b, tw, isctx, "wkd"))

            def vjob():
                for sub in range(tw // 128):
                    kt = c0 // 128 + sub
                    j = nxt()
                    for c in range(8):
                        P.mm(pp[j][:, 0:256], ht[b][:, c, sub * 128:(sub + 1) * 128], wv[:, c, :], c == 0, c == 7, ["wv", f"ht{b}"], [f"pp{j}"])
                    P.cp("scalar", VA[:, kt, 65:325].rearrange("p (g x) -> p g x", x=65)[:, :, 0:64],
                         pp[j][:, 0:256].rearrange("p (g d) -> p g d", d=64), [f"pp{j}", "VA0"], [("VA", kt)])
                    yield

            jobs.append(vjob())
            active = []
            while jobs or active:
                while jobs and len(active) < 2:
                    active.append(jobs.pop(0))
                for gen in list(active):
                    try:
                        next(gen)
                    except StopIteration:
                        active.remove(gen)
            if not isctx:
                P.dma("sync", fm(qtd[:, c0:c0 + tw]), qst[b][:, :, :tw], reads=[(f"qst{b}", oc) for oc in range(8)], writes=[("qtd", i)], sem=f"qst{b}")


def stage_attn(P, io, G, qtd, Kz, VA, xa):
    with P.phase("attn"):
        wo = P.sb([128, 8, 1024], BF16)
        P.dma("gpsimd", wo[:], fm(io["attn_wo"]), writes=["wo"], sem="wo")
        sel = P.sb([128, 2, 128], F32)
        P.dma("sync", sel[:], io["c_sel"], writes=["sel"], sem="sel")
        PT = [P.sb([128, 1024], BF16) for _ in range(3)]
        osb = [P.sb([128, 512], F32) for _ in range(2)]
        rb = [P.sb([128, 512], F32) for _ in range(2)]
        xt = P.sb([128, 8, 512], F32)
        QB = [P.sb([128, 8, 512], BF16) for _ in range(2)]
        psS = [P.ps([128, 1024], F32) for _ in range(2)]
        psO = [P.ps([128, 512], F32) for _ in range(2)]
        psB = P.ps([128, 512], F32)
        pX = [P.ps([128, 512], F32) for _ in range(1)]
        _, _, gates = mod_scalars(G, 1, 0, False)
        for k in range(2):
            P.memset("gpsimd", osb[k][:], 0.0, [f"osb{k}"])
        def loadq(qb):
            P.dma("sync", QB[qb % 2][:], fm(qtd[:, qb * 512:(qb + 1) * 512]), writes=[("QT", h, qb) for h in range(16)], sem=f"QB{qb % 2}")

        loadq(0)
        for qb in range(8):
            qsl = slice(qb * 512, (qb + 1) * 512)
            QT = QB[qb % 2]
            if qb + 1 < 8:
                loadq(qb + 1)
            P.dma("sync", xt[:], fm(xa[:, qsl]), writes=["xt"], sem="xt")
            steps = [(h, kp) for h in range(16) for kp in range(17)]

            def S(i):
                h, kp = steps[i]
                g, oc, h2 = h // 4, h // 2, h % 2
                for e_ in range(2):
                    kt = 2 * kp + e_
                    P.mm(psS[i % 2][:, e_ * 512:(e_ + 1) * 512], Kz[h2][:, g, kt * 128:(kt + 1) * 128], QT[:, oc, :], True, True,
                         ["Kz", ("QT", h, qb)], [f"psS{i % 2}"])

            def epi_a(h):
                o = h % 2
                P.cp("vector", osb[o][:], psO[o][:], [f"psO{o}"], [f"osb{o}"])

            def epi_b(h):
                oc, h2, o = h // 2, h % 2, h % 2
                hs = slice(h2 * 64, h2 * 64 + 64)
                P.mm(psB[:, :], sel[:, h2, :], osb[o][:], True, True, ["sel", f"osb{o}"], ["psB"])
                P.op("vector", lambda e, o=o, hs=hs: e.reciprocal(out=rb[o][hs, :], in_=psB[hs, :]), ["psB"], [f"rb{o}"])
                P.tt("gpsimd", QT[hs, oc, :], osb[o][hs, :], rb[o][hs, :], ALU.mult, [f"osb{o}", f"rb{o}"], [("QT", h, qb)])

            S(0)
            pend = {}
            for i, (h, kp) in enumerate(steps):
                g, h2, o = h // 4, h % 2, h % 2
                if i + 1 < len(steps):
                    S(i + 1)
                p_ = i % 3
                P.act(PT[p_][:], psS[i % 2][:, :], AF.Exp, [f"psS{i % 2}"], [f"PT{p_}"], scale=0.125)
                v0 = 65 + 65 * g if h2 == 0 else 1 + 65 * g
                for e_ in range(2):
                    kt = 2 * kp + e_
                    P.mm(psO[o][:, :], VA[:, kt, v0:v0 + 128], PT[p_][:, e_ * 512:(e_ + 1) * 512], kt == 0, kt == 33, [f"PT{p_}", "VA"], [f"psO{o}"])
                if kp == 16:
                    epi_a(h)
                    pend[i + 3] = h
                if i in pend:
                    epi_b(pend.pop(i))
            for k in sorted(pend):
                epi_b(pend[k])
            for oc in range(8):
                j = 0
                for c in range(8):
                    P.mm(pX[j][:, :], wo[:, c, oc * 128:(oc + 1) * 128], QT[:, c, :], c == 0, c == 7,
                         ["wo", ("QT", 2 * c, qb), ("QT", 2 * c + 1, qb)], [f"pX{j}"])
                P.stt(xt[:, oc, :], pX[j][:, :], gates[oc], xt[:, oc, :], ALU.mult, ALU.add, [f"pX{j}", "xt", "modv"], ["xt"])
            P.dma("sync", fm(xa[:, qsl]), xt[:], reads=["xt"], writes=[("xa", qb)], sem="xt")


IN_SHAPES = {
    "xin": [D, TT], "cvec": [128, 8, 2], "w_mod": [2, D, 6 * D], "b_mod": [2, 6 * D], "vecs": [128, NV, 8],
    "mlp_w1": [2, D, 4 * D], "mlp_w2": [2, 4 * D, D],
    "rwkv_wr": [D, D], "rwkv_wk": [D, D], "rwkv_wv": [D, D], "rwkv_wo": [D, D],
    "rwkv_w1": [2, D, 64], "rwkv_w2": [2, 64, D], "rwkv_a1": [2, D, 64], "rwkv_a2": [2, 64, D],
    "rwkv_g1": [D, 128], "rwkv_g2": [128, D], "lnw_st": [128, 8, 64], "lnb_st": [128, 8, 64],
    "attn_wq": [D, D], "attn_wkd": [D, 512], "attn_wv": [D, 256], "attn_wo": [D, D],
    "cosT": [128, T], "sinT": [128, T],
    "c_ident": [128, 128], "c_ones": [128, 128], "c_bones": [128, 128], "c_masks": [128, 4, 128],
    "c_perm": [128, 128], "c_rmask": [128, 256], "c_sel": [128, 2, 128],
}


class IO(dict):
    def __init__(self, nc):
        super().__init__()
        self.nc = nc
        self.used = []

    def __missing__(self, k):
        ap = self.nc.dram_tensor(k, IN_SHAPES[k], F32, kind="ExternalInput").ap()
        self[k] = ap
        self.used.append(k)
        return ap

    def scratch(self, name, shape, dtype):
        return self.nc.dram_tensor(name, list(shape), dtype, kind="Internal").ap()

    def output(self, name, shape, dtype=F32):
        return self.nc.dram_tensor(name, list(shape), dtype, kind="ExternalOutput").ap()


def build(stages="all", dbg=None):
    nc = bass.Bass("TRN2", target_bir_lowering=False)
    io = IO(nc)
    P = Prog(nc)
    G = {}
    outs = {}
    stage_init(P, io, G)
    xa = io.scratch("xa", [D, TT], F32)
    hb = io.scratch("hb", [D, TT], BF16)
    if stages == "t_mlp":
        outs["dbg_h"] = io.output("dbg_h", [D, TT], BF16)
        stage_norm(P, io, G, "n_t", io["xin"], ALL_TILES,
                   lambda ic: mod_scalars(G, 0, 1, ic)[0], lambda ic: mod_scalars(G, 0, 1, ic)[1],
                   lambda c0, tw, ic: fm(hb[:, c0:c0 + tw]), BF16)
        with P.phase("copy"):
            P.dma("sync", xa, io["xin"], writes=["xa"], sem="cpa")
            P.dma("sync", outs["dbg_h"], hb, writes=["o"], sem="cpb")
        stage_mlp(P, io, G, 0, ALL_TILES, xa, hb)
        outs["y"] = io.output("y", [D, TT])
        fin = [G["vec"][:, 4, c:c + 1] for c in range(8)]
        stage_norm(P, io, G, "final", xa, ALL_TILES, lambda ic: fin, lambda ic: None,
                   lambda c0, tw, ic: fm(outs["y"][:, c0:c0 + tw]), F32)
    if stages in ("all", "l0", "l1pre"):
        hp = io.scratch("hp", [D, 4608], F32)
        S = rw_scratch(io)
        with P.phase("zpad"):
            z = P.sb([128, 8, 64], F32)
            P.memset("vector", z[:], 0.0, ["z"])
            for k, o in enumerate((0, 64 + T, 4224, 4288 + C)):
                P.dma("sync", fm(hp[:, o:o + 64]), z[:], reads=["z"], writes=[("hpz", k)], sem=f"z{k}")

        def hdst(c0, tw, ic):
            o = 4288 if ic else 64 + c0
            return fm(hp[:, o:o + tw])

        def hbdst(c0, tw, ic):
            return fm(hb[:, c0:c0 + tw])

        def ms(l, kind, which):
            return lambda ic: mod_scalars(G, l, kind, ic)[which]

        stage_norm(P, io, G, "n_mix0", io["xin"], ALL_TILES, ms(0, 0, 0), ms(0, 0, 1), hdst, F32)
        stage_rwkv1a(P, io, G, hp, S)
        stage_rwkv1b(P, io, G, S)
        stage_rwkv2(P, io, G, S, io["xin"], xa)
        stage_norm(P, io, G, "n_mlp0", xa, ALL_TILES, ms(0, 1, 0), ms(0, 1, 1), hbdst, BF16)
        stage_mlp(P, io, G, 0, ALL_TILES, xa, hb)
        if stages == "l0":
            outs["y"] = io.output("y", [D, TT])
            with P.phase("copyout"):
                P.dma("sync", outs["y"], xa, writes=["o"], sem="cpa")
        else:
            stage_norm(P, io, G, "n_mix1", xa, ALL_TILES, ms(1, 0, 0), ms(1, 0, 1), hbdst, BF16)
            with P.scope():
                QT = io.scratch("qtd", [D, T], BF16)
                Kz = [P.ssb([128, 4, TT], BF16, f"Kz{k}") for k in range(2)]
                VA = P.ssb([128, 34, 390], BF16, "VA")
                stage_qkv(P, io, G, hb, QT, Kz, VA)
                stage_attn(P, io, G, QT, Kz, VA, xa)
            if stages == "l1pre":
                outs["y"] = io.output("y", [D, TT])
                with P.phase("copyout"):
                    P.dma("sync", outs["y"], xa, writes=["o"], sem="cpa")
            else:
                stage_norm(P, io, G, "n_mlp1", xa, LAT_TILES, ms(1, 1, 0), ms(1, 1, 1), hbdst, BF16)
                stage_mlp(P, io, G, 1, LAT_TILES, xa, hb)
                outs["y"] = io.output("y", [D, T])
                fin = [G["vec"][:, 4, c:c + 1] for c in range(8)]
                stage_norm(P, io, G, "final", xa, LAT_TILES, lambda ic: fin, lambda ic: None,
                           lambda c0, tw, ic: fm(outs["y"][:, c0:c0 + tw]), F32)
    if stages == "t_rwkv":
        hp = io.scratch("hp", [D, 4608], F32)
        S = rw_scratch(io)
        with P.phase("zpad"):
            z = P.sb([128, 8, 64], F32)
            P.memset("vector", z[:], 0.0, ["z"])
            for k, o in enumerate((0, 64 + T, 4224, 4288 + C)):
                P.dma("sync", fm(hp[:, o:o + 64]), z[:], reads=["z"], writes=[("hpz", k)], sem=f"z{k}")
        def hdst(c0, tw, ic):
            o = 4288 if ic else 64 + c0
            return fm(hp[:, o:o + tw])
        stage_norm(P, io, G, "n_mix0", io["xin"], ALL_TILES,
                   lambda ic: mod_scalars(G, 0, 0, ic)[0], lambda ic: mod_scalars(G, 0, 0, ic)[1], hdst, F32)
        stage_rwkv1(P, io, G, hp, S)
        stage_rwkv2(P, io, G, S, io["xin"], xa)
        outs["y"] = io.output("y", [D, TT])
        with P.phase("copyout"):
            P.dma("sync", outs["y"], xa, writes=["o"], sem="cpa")
    P.close()
    return nc, io.used, list(outs.keys()), P


def fmv(v):
    return np.ascontiguousarray(np.asarray(v, np.float32).reshape(8, 128).T)


def host_consts():
    c = {}
    c["c_ident"] = np.eye(128, dtype=np.float32)
    c["c_ones"] = np.ones((128, 128), np.float32)
    blk = np.zeros((128, 128), np.float32)
    blk[:64, :64] = 1
    blk[64:, 64:] = 1
    c["c_bones"] = blk
    i = np.arange(64)
    us = (i[:, None] < i[None, :]).astype(np.float32)
    ui = (i[:, None] <= i[None, :]).astype(np.float32)
    m = np.zeros((128, 4, 128), np.float32)
    for k, mk in enumerate([us, ui, us.T, ui.T]):
        m[:64, k, :64] = mk
        m[64:, k, 64:] = mk
    c["c_masks"] = m
    Pm = np.zeros((128, 128), np.float32)
    for d in range(128):
        if d % 32 < 16:
            Pm[d, d + 16] = -1.0
        else:
            Pm[d, d - 16] = 1.0
    c["c_perm"] = np.ascontiguousarray(Pm.T)
    sel = np.zeros((128, 2, 128), np.float32)
    sel[64, 0, :] = 1.0
    sel[63, 1, :] = 1.0
    c["c_sel"] = sel
    rm = np.ones((128, 256), np.float32)
    rm[:, ::64] = 0
    c["c_rmask"] = rm
    t = np.arange(T)
    row = (t // 64).astype(np.float32)
    col = (t % 64).astype(np.float32)
    freqs = (np.float32(10000.0) ** (-np.arange(0, 32, 2, dtype=np.float32) / np.float32(32))).astype(np.float32)
    ang = np.zeros((64, T), np.float32)
    for d in range(64):
        pos = row if d < 32 else col
        ang[d] = pos * freqs[d % 16]
    c["cosT"] = np.ascontiguousarray(np.concatenate([np.cos(ang), np.cos(ang)], 0).astype(np.float32))
    c["sinT"] = np.ascontiguousarray(np.concatenate([np.sin(ang), np.sin(ang)], 0).astype(np.float32))
    return c


def host_inputs(inp, b):
    f = lambda k: np.asarray(inp[k], np.float32)
    d = {}
    d["xin"] = np.ascontiguousarray(np.concatenate([f("x")[b].T, f("ctx")[b].T], axis=1))
    d["cvec"] = np.ascontiguousarray(np.stack([fmv(f("c")[b]), fmv(f("c_ctx"))], axis=-1))
    return d


def host_shared(inp):
    f = lambda k: np.asarray(inp[k], np.float32)
    s = dict(host_consts())
    s["w_mod"] = f("w_mod")
    s["b_mod"] = f("b_mod")
    vl = [f("norm_mix")[0], f("norm_mix")[1], f("norm_mlp")[0], f("norm_mlp")[1], f("final_norm")]
    vl += [f("rwkv_mu")[0, j] for j in range(6)]
    vl += [f("rwkv_w0")[0, 0], f("rwkv_w0")[0, 1], f("rwkv_a0")[0, 0], f("rwkv_a0")[0, 1]]
    vl += [f("rwkv_k_k")[0], f("rwkv_k_a")[0], np.zeros(D, np.float32), f("rwkv_r_k")[0].reshape(-1)]
    vl += [np.tile(f("attn_q_norm")[0], 16), np.tile(f("attn_k_norm")[0], 16)]
    assert len(vl) == NV
    s["vecs"] = np.ascontiguousarray(np.stack([fmv(v) for v in vl], axis=1))
    s["mlp_w1"] = f("mlp_w1")
    s["mlp_w2"] = f("mlp_w2")
    for k in ("wr", "wk", "wv", "wo", "w1", "w2", "a1", "a2", "g1", "g2"):
        s["rwkv_" + k] = f("rwkv_" + k)[0]
    lw = f("rwkv_ln_w")[0].reshape(8, 2, 64)
    lb = f("rwkv_ln_b")[0].reshape(8, 2, 64)
    s["lnw_st"] = np.ascontiguousarray(np.repeat(lw.transpose(1, 0, 2), 64, axis=0))
    s["lnb_st"] = np.ascontiguousarray(np.repeat(lb.transpose(1, 0, 2), 64, axis=0))
    wqkv = f("attn_wqkv")[0]
    s["attn_wq"] = np.ascontiguousarray(wqkv[:, :1024])
    wk = wqkv[:, 1024:1280].reshape(D, 4, 64)
    s["attn_wkd"] = np.ascontiguousarray(np.concatenate([wk, wk], axis=2).reshape(D, 512))
    s["attn_wv"] = np.ascontiguousarray(wqkv[:, 1280:1536])
    s["attn_wo"] = f("attn_wo")[0]
    return s


_CACHE = {}


def kernel(**inputs):
    if "prog" not in _CACHE:
        _CACHE["prog"] = build("all")
    nc, used, outnames, _ = _CACHE["prog"]
    shared = host_shared(inputs)
    in_maps = []
    for b in range(NCORES):
        hi = host_inputs(inputs, b)
        hi.update(shared)
        in_maps.append({k: hi[k] for k in used})
    res = run_bass_kernel_spmd(nc, in_maps, core_ids=list(range(NCORES)))
    out = np.stack([np.ascontiguousarray(res.results[b]["y"].T) for b in range(NCORES)], axis=0)
    return out.astype(np.float32)
```

```python
from contextlib import ExitStack, contextmanager
import re as re_mod
import numpy as np
import concourse.bass as bass
import concourse.mybir as mybir
from concourse.bass_utils import run_bass_kernel_spmd

F32 = mybir.dt.float32
BF16 = mybir.dt.bfloat16
AF = mybir.ActivationFunctionType
ALU = mybir.AluOpType
AX = mybir.AxisListType

D = 1024
T = 4096
C = 256
TT = T + C
NCORES = 8
C0 = float(np.exp(-0.5))
NV = 21
ENGS = ("tensor", "vector", "scalar", "gpsimd", "sync")


class Prog:
    def __init__(self, nc):
        self.nc = nc
        self.ges = ExitStack()
        self.sems = {}
        self.cnt = {}
        self.dpool = {False: [], True: []}
        self.seen = {e: {} for e in ENGS}
        self.n = 0
        self.pes = None
        self.total_ops = 0

    def _alloc(self, es, fn, shape, dtype, name):
        self.n += 1
        return es.enter_context(fn(name or f"t{self.n}", list(shape), dtype))

    def gsb(self, shape, dtype, name=None):
        return self._alloc(self.ges, self.nc.sbuf_tensor, shape, dtype, name)

    def sb(self, shape, dtype, name=None):
        return self._alloc(self.pes, self.nc.sbuf_tensor, shape, dtype, name)

    @contextmanager
    def scope(self):
        self.ses = ExitStack()
        yield self
        self.ses.close()
        self.ses = None

    def ssb(self, shape, dtype, name=None):
        return self._alloc(self.ses, self.nc.sbuf_tensor, shape, dtype, name)

    def ps(self, shape, dtype, name=None):
        return self._alloc(self.pes, self.nc.psum_tensor, shape, dtype, name)

    @contextmanager
    def phase(self, name):
        self.ops = []
        self.last_w = {}
        self.readers = {}
        self.last_dma = {}
        self.pes = ExitStack()
        self.pname = name
        yield self
        self._emit()
        self.pes.close()
        self.pes = None

    _PSUM_RE = re_mod.compile(r"^(pp|pa|pb|pq|pf|ps\w*|pX|py|pS|ptr|pw)\d*$")

    ns = None
    ns_set = frozenset()

    def _deps(self, reads, writes):
        if self.ns is not None:
            reads = tuple((r, self.ns) if r in self.ns_set else r for r in reads)
            writes = tuple((w, self.ns) if w in self.ns_set else w for w in writes)
        extra = tuple(r for r in reads if isinstance(r, str) and self._PSUM_RE.match(r) and r not in writes)
        if extra:
            writes = tuple(writes) + extra
        deps = {}
        for r in reads:
            if r in self.last_w:
                deps.setdefault(self.last_w[r], set()).add("RAW")
        for w in writes:
            if w in self.last_w:
                deps.setdefault(self.last_w[w], set()).add("WAW")
            for rd in self.readers.get(w, ()):
                deps.setdefault(rd, set()).add("WAR")
        idx = len(self.ops)
        for r in reads:
            self.readers.setdefault(r, []).append(idx)
        for w in writes:
            self.last_w[w] = idx
            self.readers[w] = []
        return deps

    def op(self, eng, fn, reads=(), writes=()):
        deps = self._deps(tuple(reads), tuple(writes))
        self.ops.append(dict(eng=eng, fn=fn, deps=deps, dma=None))
        return len(self.ops) - 1

    def dma(self, queue, out, in_, reads=(), writes=(), sem=None):
        deps = self._deps(tuple(reads), tuple(writes))
        prev = self.last_dma.get(sem)
        if prev is not None:
            deps.setdefault(prev, set()).add("SER")
        idx = len(self.ops)
        self.last_dma[sem] = idx
        self.ops.append(dict(eng=queue, fn=lambda e: e.dma_start(out=out, in_=in_), deps=deps, dma=sem))
        return idx

    def _emit(self):
        nc = self.nc
        ops = self.ops
        if self.last_dma:
            ops.append(dict(eng="sync", fn=None, deps={i: {"FIN"} for i in self.last_dma.values()}, dma=None))
        self.total_ops += len(ops)

        def needs_wait(x, d, kinds):
            if d["dma"] is not None or x["dma"] is not None:
                return True
            if d["eng"] != x["eng"]:
                return True
            if x["eng"] == "tensor":
                return False
            return bool(kinds & {"RAW", "FIN"})

        signal = [False] * len(ops)
        for x in ops:
            for di, kinds in x["deps"].items():
                d = ops[di]
                if d["dma"] is None and needs_wait(x, d, kinds):
                    signal[di] = True
        dkeys = {}
        nk = {False: 0, True: 0}
        for o in ops:
            if o["dma"] is not None and o["dma"] not in dkeys:
                sw = o["eng"] == "gpsimd"
                dkeys[o["dma"]] = (sw, nk[sw])
                nk[sw] += 1
        for sw in (False, True):
            while len(self.dpool[sw]) < nk[sw]:
                h = self.ges.enter_context(nc.semaphore(f"dq{int(sw)}_{len(self.dpool[sw])}"))
                self.dpool[sw].append([h, 0])
        for e in ENGS:
            if e not in self.sems:
                self.sems[e] = self.ges.enter_context(nc.semaphore(f"e_{e}"))
        token = [None] * len(ops)
        for i, o in enumerate(ops):
            if o["dma"] is not None:
                dk = dkeys[o["dma"]]
                slot = self.dpool[dk[0]][dk[1]]
                slot[1] += 16
                token[i] = (("d", dk), slot[1])
            elif signal[i]:
                self.cnt[o["eng"]] = self.cnt.get(o["eng"], 0) + 1
                token[i] = (("e", o["eng"]), self.cnt[o["eng"]])
        per_eng = {e: [] for e in ENGS}
        for i, o in enumerate(ops):
            per_eng[o["eng"]].append(i)

        def semh(key):
            return self.dpool[key[1][0]][key[1][1]][0] if key[0] == "d" else self.sems[key[1]]

        def run(engname, eng):
            seen = self.seen[engname]
            for i in per_eng[engname]:
                o = ops[i]
                waits = {}
                for di, kinds in o["deps"].items():
                    d = ops[di]
                    if not needs_wait(o, d, kinds):
                        continue
                    key, val = token[di]
                    if waits.get(key, 0) < val:
                        waits[key] = val
                for key, val in waits.items():
                    if seen.get(key, 0) >= val:
                        continue
                    seen[key] = val
                    eng.wait_ge(semh(key), val)
                if o["fn"] is None:
                    continue
                ins = o["fn"](eng)
                if o["dma"] is not None:
                    ins.then_inc(semh(token[i][0]), 16)
                elif signal[i]:
                    ins.then_inc(self.sems[engname], 1)

        with nc.Block() as block:
            @block.sync
            def _(e):
                run("sync", e)

            @block.tensor
            def _(e):
                run("tensor", e)

            @block.vector
            def _(e):
                run("vector", e)

            @block.scalar
            def _(e):
                run("scalar", e)

            @block.gpsimd
            def _(e):
                run("gpsimd", e)

    def close(self):
        self.ges.close()

    def mm(self, out, lhsT, rhs, start, stop, r, w):
        self.op("tensor", lambda e: e.matmul(out, lhsT=lhsT, rhs=rhs, start=start, stop=stop), r, w)

    def tr(self, out, in_, ident, r, w):
        self.op("tensor", lambda e: e.transpose(out, in_, ident), r, w)

    def tt(self, eng, out, in0, in1, op, r, w):
        self.op(eng, lambda e: e.tensor_tensor(out=out, in0=in0, in1=in1, op=op), r, w)

    def ts(self, eng, out, in0, s1, s2, op0, op1, r, w):
        if op1 is None:
            self.op(eng, lambda e: e.tensor_scalar(out=out, in0=in0, scalar1=s1, scalar2=None, op0=op0), r, w)
        else:
            self.op(eng, lambda e: e.tensor_scalar(out=out, in0=in0, scalar1=s1, scalar2=s2, op0=op0, op1=op1), r, w)

    def stt(self, out, in0, scalar, in1, op0, op1, r, w):
        self.op("vector", lambda e: e.scalar_tensor_tensor(out=out, in0=in0, scalar=scalar, in1=in1, op0=op0, op1=op1), r, w)

    def act(self, out, in_, func, r, w, bias=None, scale=None):
        kw = {}
        if bias is not None:
            kw["bias"] = bias
        if scale is not None:
            kw["scale"] = scale
        self.op("scalar", lambda e: e.activation(out=out, in_=in_, func=func, **kw), r, w)

    def cp(self, eng, out, in_, r, w):
        if eng == "scalar":
            self.op(eng, lambda e: e.activation(out=out, in_=in_, func=AF.Copy), r, w)
        else:
            self.op(eng, lambda e: e.tensor_copy(out=out, in_=in_), r, w)

    def memset(self, eng, ap, val, w):
        self.op(eng, lambda e: e.memset(ap, val), (), w)


def fm(ap2d):
    return ap2d.rearrange("(c p) n -> p c n", p=128)


LAT_TILES = [(i * 512, 512, False) for i in range(8)]
ALL_TILES = LAT_TILES + [(T, 256, True)]


def stage_init(P, io, G):
    nc = P.nc
    G["identf"] = P.gsb([128, 128], F32, "identf")
    G["identb"] = P.gsb([128, 128], BF16, "identb")
    G["onesb"] = P.gsb([128, 128], BF16, "onesb")
    G["bones"] = P.gsb([128, 128], BF16, "bones")
    G["masks"] = P.gsb([128, 4, 128], BF16, "masks")
    G["perm"] = P.gsb([128, 128], BF16, "perm")
    G["rmask"] = P.gsb([128, 256], F32, "rmask")
    G["vec"] = P.gsb([128, NV, 8], F32, "vec")
    G["modv"] = P.gsb([128, 2, 6, 8, 2], F32, "modv")
    G["gg"] = P.gsb([128, 2, 2, 8, 2], F32, "gg")
    with P.phase("init"):
        P.dma("sync", G["identf"][:], io["c_ident"], writes=["identf"], sem="identf")
        P.dma("sync", G["rmask"][:], io["c_rmask"], writes=["rmask"], sem="rmask")
        P.dma("sync", G["vec"][:], io["vecs"], writes=["vec"], sem="vec")
        P.dma("gpsimd", G["identb"][:], io["c_ident"], writes=["identb"], sem="identb")
        P.dma("gpsimd", G["onesb"][:], io["c_ones"], writes=["onesb"], sem="onesb")
        P.dma("gpsimd", G["bones"][:], io["c_bones"], writes=["bones"], sem="bones")
        P.dma("gpsimd", G["masks"][:], io["c_masks"], writes=["masks"], sem="masks")
        P.dma("gpsimd", G["perm"][:], io["c_perm"], writes=["perm"], sem="perm")
        vec = G["vec"]
        P.ts("vector", vec[:, 17, :], vec[:, 16, :], -1.0, 1.0, ALU.mult, ALU.add, ["vec"], ["vec"])
        sv = P.sb([128, 8, 2], F32)
        svs = P.sb([128, 8, 2], F32)
        P.dma("sync", sv[:], io["cvec"], writes=["sv"], sem="sv")
        P.act(svs[:], sv[:], AF.Silu, ["sv"], ["svs"])
        brow = P.sb([2, 2 * 6144], F32)
        row = P.sb([2, 2 * 6144], F32)
        P.dma("sync", brow[:], io["b_mod"].rearrange("l n -> (l n)").partition_broadcast(2), writes=["brow"], sem="brow")
        wt = [P.sb([128, 8, 512], F32) for _ in range(2)]
        psr = [P.ps([128, 512], F32) for _ in range(2)]
        pst = P.ps([128, 512], F32)
        k = 0
        for l in range(2):
            for nb in range(12):
                b = k % 2
                k += 1
                P.dma("sync", wt[b][:], fm(io["w_mod"][l, :, nb * 512:(nb + 1) * 512]), writes=[f"wt{b}"], sem=f"wt{b}")
                for c in range(8):
                    P.mm(psr[b][0:2, :], svs[:, c, :], wt[b][:, c, :], c == 0, c == 7, ["svs", f"wt{b}"], [f"psr{b}"])
                o = l * 6144 + nb * 512
                P.tt("vector", row[:, o:o + 512], psr[b][0:2, :], brow[:, o:o + 512], ALU.add, [f"psr{b}", "brow"], ["row"])
        for l in range(2):
            for blk in range(48):
                o = l * 6144 + blk * 128
                P.tr(pst[:, l * 96 + blk * 2:l * 96 + blk * 2 + 2], row[0:2, o:o + 128], G["identf"][0:2, 0:2], ["row", "identf"], ["pst"])
        P.cp("vector", G["modv"][:].rearrange("p l m c j -> p (l m c j)"), pst[:, 0:192], ["pst"], ["modv"])
        modv, gg = G["modv"], G["gg"]
        for l in range(2):
            for kind in range(2):
                sc = modv[:, l, 1 + 3 * kind, :, :]
                nv = vec[:, (0 if kind == 0 else 2) + l, :].unsqueeze(2).broadcast_to([128, 8, 2])
                P.ts("vector", gg[:, l, kind, :, :], sc, 1.0, None, ALU.add, None, ["modv"], ["gg"])
                P.tt("vector", gg[:, l, kind, :, :], gg[:, l, kind, :, :], nv, ALU.mult, ["gg", "vec"], ["gg"])


def mod_scalars(G, l, kind, isctx):
    j = 1 if isctx else 0
    gains = [G["gg"][:, l, kind, c, j:j + 1] for c in range(8)]
    shifts = [G["modv"][:, l, 3 * kind, c, j:j + 1] for c in range(8)]
    gates = [G["modv"][:, l, 3 * kind + 2, c, j:j + 1] for c in range(8)]
    return gains, shifts, gates


def stage_norm(P, io, G, name, src, tiles, gains_fn, shifts_fn, dst_fn, out_dtype):
    with P.phase(name):
        xt = [P.sb([128, 8, 512], F32) for _ in range(2)]
        sq = P.sb([128, 8, 512], BF16)
        lnv = P.sb([128, 512], F32)
        rstd = P.sb([128, 512], F32)
        tmp = [P.sb([128, 512], F32) for _ in range(2)]
        ho = [P.sb([128, 8, 512], out_dtype) for _ in range(2)]
        ps = [P.ps([128, 512], F32) for _ in range(2)]

        def load(i):
            c0, tw, _ = tiles[i]
            b = i % 2
            P.dma("sync", xt[b][:, :, :tw], fm(src[:, c0:c0 + tw]), writes=[f"xt{b}"], sem=f"xt{b}")

        load(0)
        for i, (c0, tw, isctx) in enumerate(tiles):
            b = i % 2
            if i + 1 < len(tiles):
                load(i + 1)
            gains = gains_fn(isctx)
            shifts = shifts_fn(isctx)
            P.act(sq[:, :, :tw], xt[b][:, :, :tw], AF.Square, [f"xt{b}"], ["sq"])
            for c in range(8):
                P.mm(ps[b][:, :tw], G["onesb"][:], sq[:, c, :tw], c == 0, c == 7, ["sq", "onesb"], [f"ps{b}"])
            P.act(lnv[:, :tw], ps[b][:, :tw], AF.Ln, [f"ps{b}"], ["lnv"], bias=1e-6, scale=1.0 / D)
            P.act(rstd[:, :tw], lnv[:, :tw], AF.Exp, ["lnv"], ["rstd"], scale=-0.5)
            for c in range(8):
                if shifts is None:
                    P.stt(ho[b][:, c, :tw], xt[b][:, c, :tw], gains[c], rstd[:, :tw], ALU.mult, ALU.mult,
                          [f"xt{b}", "rstd", "vec", "gg"], [f"ho{b}"])
                else:
                    t = tmp[c % 2]
                    P.stt(t[:, :tw], xt[b][:, c, :tw], gains[c], rstd[:, :tw], ALU.mult, ALU.mult,
                          [f"xt{b}", "rstd", "vec", "gg"], [f"tmp{c % 2}"])
                    P.act(ho[b][:, c, :tw], t[:, :tw], AF.Identity, [f"tmp{c % 2}", "modv"], [f"ho{b}"], bias=shifts[c])
            P.dma("sync", dst_fn(c0, tw, isctx), ho[b][:, :, :tw], reads=[f"ho{b}"], writes=[("dst", i)], sem=f"ho{b}")


def stage_mlp(P, io, G, l, tiles, xa, hb):
    for half in range(2):
        with P.phase(f"mlp{l}{half}"):
            w1 = P.sb([128, 8, 2048], BF16)
            w2 = P.sb([128, 16, 1024], BF16)
            for q in range(2):
                P.dma("gpsimd", w1[:, :, q * 1024:(q + 1) * 1024],
                      fm(io["mlp_w1"][l, :, half * 2048 + q * 1024: half * 2048 + (q + 1) * 1024]), writes=["w1"], sem=f"w1{q}")
                P.dma("gpsimd", w2[:, q * 8:(q + 1) * 8, :],
                      io["mlp_w2"][l, half * 2048 + q * 1024: half * 2048 + (q + 1) * 1024, :].rearrange("(f p) n -> p f n", p=128),
                      writes=["w2"], sem=f"w2{q}")
            xt = [P.sb([128, 8, 512], F32) for _ in range(2)]
            ht = [P.sb([128, 8, 512], BF16) for _ in range(2)]
            h1 = P.sb([128, 16, 512], BF16)
            r1 = [P.sb([128, 512], F32) for _ in range(2)]
            ps = [P.ps([128, 512], F32) for _ in range(4)]

            def load(i):
                c0, tw, _ = tiles[i]
                b = i % 2
                P.dma("sync", ht[b][:, :, :tw], fm(hb[:, c0:c0 + tw]), writes=[f"ht{b}"], sem=f"ht{b}")
                P.dma("sync", xt[b][:, :, :tw], fm(xa[:, c0:c0 + tw]), reads=[("xa", i)], writes=[f"xt{b}"], sem=f"xt{b}")

            load(0)
            for i, (c0, tw, isctx) in enumerate(tiles):
                b = i % 2
                if i + 1 < len(tiles):
                    load(i + 1)
                _, _, gates = mod_scalars(G, l, 1, isctx)
                for fc in range(16):
                    pb = fc % 2
                    for c in range(8):
                        P.mm(ps[pb][:, :tw], w1[:, c, fc * 128:(fc + 1) * 128], ht[b][:, c, :tw], c == 0, c == 7,
                             ["w1", f"ht{b}"], [f"ps{pb}"])
                    P.act(r1[pb][:, :tw], ps[pb][:, :tw], AF.Relu, [f"ps{pb}"], [f"r1{pb}"])
                    P.tt("gpsimd", h1[:, fc, :tw], r1[pb][:, :tw], r1[pb][:, :tw], ALU.mult, [f"r1{pb}"], [("h1", fc)])
                for oc in range(8):
                    pb = 2 + oc % 2
                    for fc in range(16):
                        P.mm(ps[pb][:, :tw], w2[:, fc, oc * 128:(oc + 1) * 128], h1[:, fc, :tw], fc == 0, fc == 15,
                             ["w2", ("h1", fc)], [f"ps{pb}"])
                    P.stt(xt[b][:, oc, :tw], ps[pb][:, :tw], gates[oc], xt[b][:, oc, :tw], ALU.mult, ALU.add,
                          [f"ps{pb}", f"xt{b}", "modv"], [f"xt{b}"])
                P.dma("sync", fm(xa[:, c0:c0 + tw]), xt[b][:, :, :tw], reads=[f"xt{b}"], writes=[("xa", i)], sem=f"xt{b}")


RW_ORDER1 = [(True, 0)] + [(False, i) for i in range(16)]
RW_ORDER2 = [(True, 0)] + [(False, i) for i in range(15, -1, -1)]


def rw_scratch(io):
    S = {}
    S["yp"] = io.scratch("rw_yp", [17, 8, 128, 256], F32)
    S["sadd"] = io.scratch("rw_sadd", [17, 8, 128, 256], F32)
    S["vst"] = io.scratch("rw_vst", [17, 8, 128, 256], F32)
    S["gst"] = io.scratch("rw_gst", [17, 8, 128, 256], F32)
    S["gyb"] = io.scratch("rw_gyb", [17, 8, 128, 512], BF16)
    S["gsb"] = io.scratch("rw_gsb", [17, 8, 128, 512], BF16)
    S["gamb"] = io.scratch("rw_gamb", [17, 128, 32], F32)
    S["bon"] = io.scratch("rw_bon", [17, 128, 32], F32)
    S["ops"] = io.scratch("rw_ops", [17, 8, 128, 2048], BF16)
    S["vb"] = io.scratch("rw_vb", [17, 8, 128, 256], BF16)
    S["gam"] = io.scratch("rw_gam", [17, 8, 128, 8], F32)
    return S


def stage_rwkv1(P, io, G, hp, S, dbg=None):
    vec, masks, identb, identf, bones, onesb, rmask = (G[k] for k in ("vec", "masks", "identb", "identf", "bones", "onesb", "rmask"))
    with P.phase("rwkv1"):
        wr = P.sb([128, 8, 1024], BF16)
        wk = P.sb([128, 8, 1024], BF16)
        wv = P.sb([128, 8, 1024], BF16)
        for w, nm in ((wr, "rwkv_wr"), (wk, "rwkv_wk"), (wv, "rwkv_wv")):
            P.dma("gpsimd", w[:], fm(io[nm]), writes=[nm], sem=nm)
        lw1 = P.sb([128, 8, 128], BF16)
        la1 = P.sb([128, 8, 128], BF16)
        g1 = P.sb([128, 8, 128], BF16)
        for d in range(2):
            P.dma("gpsimd", lw1[:, :, d * 64:(d + 1) * 64], io["rwkv_w1"][d].rearrange("(c p) j -> p c j", p=128), writes=["lw1"], sem=f"lw1{d}")
            P.dma("gpsimd", la1[:, :, d * 64:(d + 1) * 64], io["rwkv_a1"][d].rearrange("(c p) j -> p c j", p=128), writes=["la1"], sem=f"la1{d}")
        P.dma("gpsimd", g1[:], io["rwkv_g1"].rearrange("(c p) j -> p c j", p=128), writes=["g1"], sem="g1")
        w2s = P.sb([128, 1024], BF16)
        a2s = P.sb([128, 1024], BF16)
        g2 = P.sb([128, 1024], BF16)
        P.dma("gpsimd", w2s[:], io["rwkv_w2"].rearrange("d j f -> (d j) f"), writes=["w2s"], sem="w2s")
        P.dma("gpsimd", a2s[:], io["rwkv_a2"].rearrange("d j f -> (d j) f"), writes=["a2s"], sem="a2s")
        P.dma("gpsimd", g2[:], io["rwkv_g2"], writes=["g2"], sem="g2")

        hh = P.sb([128, 8, 384], F32)
        xx = P.sb([128, 8, 256], F32)
        xr = P.sb([128, 8, 256], BF16)
        xk = P.sb([128, 8, 256], BF16)
        xv = P.sb([128, 8, 256], BF16)
        xrot = P.sb([128, 8, 256], BF16)
        lwt = P.sb([128, 256], BF16)
        lat = P.sb([128, 256], BF16)
        sg = P.sb([128, 256], BF16)
        f32t = {}
        for nm in ("r", "k", "sw0", "sw1", "ag0", "ag1", "kq", "lnv", "rs", "kkn", "fac", "kd0", "kd1", "b0", "b1",
                   "L", "Lx", "Lb", "E1", "E2", "E3", "ks"):
            f32t[nm] = P.sb([128, 256], F32, "t_" + nm)
        sqb = P.sb([128, 256], BF16)
        RK = P.sb([128, 4, 2, 64], BF16)
        VTbd = P.sb([128, 4, 128], F32)
        GTbd = P.sb([128, 4, 128], F32)
        Vf = P.sb([128, 4, 64], F32)
        Gf = P.sb([128, 4, 64], F32)
        YPs = P.sb([128, 4, 64], F32)
        SAs = P.sb([128, 4, 64], F32)
        gamb_t = P.sb([128, 8, 4], F32)
        bon_t = P.sb([128, 8, 4], F32)
        Sf = P.sb([128, 8, 64], BF16)
        ARq = [[P.sb([128, 4, 2, 128], BF16, f"AR{q}{d}") for d in range(2)] for q in range(2)]
        KTq = [[P.sb([128, 4, 128], BF16, f"KT{q}{d}") for d in range(2)] for q in range(2)]
        BTq = [[P.sb([128, 4, 128], BF16, f"BT{q}{d}") for d in range(2)] for q in range(2)]
        Vbq = [P.sb([128, 4, 64], BF16, f"Vb{q}") for q in range(3)]
        gamq = [[P.sb([128, 4], F32, f"gam{q}{d}") for d in range(2)] for q in range(3)]
        inv = []
        for d in range(2):
            st = {}
            for nm, shp in (("Atok", [128, 4, 128]), ("Btok", [128, 4, 128]), ("MQ", [128, 4, 256]), ("MWa", [128, 4, 2, 128]),
                            ("MWb", [128, 4, 2, 128]), ("MTa", [128, 4, 128]), ("MTb", [128, 4, 128])):
                st[nm] = P.sb(shp, BF16, f"i{d}_{nm}")
            inv.append(st)
        fin = []
        for q in range(2):
            row = []
            for d in range(2):
                st = {}
                for nm, shp in (("Ktok", [128, 4, 128]), ("NP", [128, 4, 256]), ("XW", [128, 4, 256]), ("NVb", [128, 4, 64]),
                                ("GY", [128, 4, 128]), ("GS", [128, 4, 128])):
                    st[nm] = P.sb(shp, BF16, f"f{q}{d}_{nm}")
                row.append(st)
            fin.append(row)
        ppt = [P.ps([128, 512], F32) for _ in range(2)]
        pp = [t_[:, 0:256] for t_ in ppt]
        pf = P.ps([128, 512], F32)
        pb = [P.ps([128, 512], F32) for _ in range(5)]
        cnt = {"pp": 0, "pb": 0}
        nmod = {"pp": 2, "pb": 5}

        def nxt(kind):
            i = cnt[kind] % nmod[kind]
            cnt[kind] += 1
            return i

        for q in range(2):
            for d in range(2):
                P.memset("gpsimd", ARq[q][d][:], 0.0, [f"AR{q}{d}"])
                P.memset("gpsimd", KTq[q][d][:], 0.0, [f"KT{q}{d}"])
                P.memset("gpsimd", BTq[q][d][:], 0.0, [f"BT{q}{d}"])
        P.memset("gpsimd", RK[:], 0.0, ["RK"])
        P.memset("gpsimd", VTbd[:], 0.0, ["VTbd"])
        P.memset("gpsimd", GTbd[:], 0.0, ["GTbd"])
        P.memset("gpsimd", Sf[:], 0.0, [("Sf", p) for p in range(8)])

        def v3(ap):
            return ap.rearrange("p (u s) -> p u s", s=64)

        def u128(ap):
            return ap.rearrange("p (u x) -> p u x", x=128)

        def load_hh(ti):
            isctx, idx = RW_ORDER1[ti]
            off = 4288 if isctx else 64 + 256 * idx
            P.dma("sync", hh[:], fm(hp[:, off - 64: off + 320]), writes=["hh"], sem="hh")

        def proj8(w_cols_fn, xb, bn, extra_r):
            i = nxt("pp")
            for c in range(8):
                P.mm(pp[i], w_cols_fn(c), xb[:, c, :], c == 0, c == 7, [(bn, c)] + extra_r, [f"pp{i}"])
            return i

        def tprep(ti):
            isctx, idx = RW_ORDER1[ti]
            hc = hh[:, :, 64:320]
            XXW = [("xx", c) for c in range(8)]
            if not isctx:
                h4 = hh[:, :, 64:320].rearrange("p c (r w) -> p c r w", w=64)
                x4 = xx[:].rearrange("p c (r w) -> p c r w", w=64)
                P.tt("vector", x4[:, 0:2, :, 1:64], h4[:, 0:2, :, 0:63], h4[:, 0:2, :, 1:64], ALU.subtract, ["hh"], XXW[0:2])
                P.ts("gpsimd", x4[:, 0:2, :, 0:1], h4[:, 0:2, :, 0:1], -1.0, 0.0, ALU.mult, ALU.add, ["hh"], [("xxe", 0)])
                P.tt("vector", x4[:, 2:4, :, 0:63], h4[:, 2:4, :, 1:64], h4[:, 2:4, :, 0:63], ALU.subtract, ["hh"], XXW[2:4])
                P.ts("gpsimd", x4[:, 2:4, :, 63:64], h4[:, 2:4, :, 63:64], -1.0, 0.0, ALU.mult, ALU.add, ["hh"], [("xxe", 1)])
                P.tt("gpsimd", xx[:, 4:6, :], hh[:, 4:6, 0:256], hh[:, 4:6, 64:320], ALU.subtract, ["hh"], XXW[4:6])
                P.tt("gpsimd", xx[:, 6:8, :], hh[:, 6:8, 128:384], hh[:, 6:8, 64:320], ALU.subtract, ["hh"], XXW[6:8])
            else:
                P.tt("vector", xx[:, 0:4, :], hh[:, 0:4, 63:319], hh[:, 0:4, 64:320], ALU.subtract, ["hh"], XXW[0:4] + [("xxe", 0)])
                P.tt("gpsimd", xx[:, 4:8, :], hh[:, 4:8, 65:321], hh[:, 4:8, 64:320], ALU.subtract, ["hh"], XXW[4:8] + [("xxe", 1)])
            yield

            def mk_xj(j, buf, bn):
                for c in range(8):
                    P.stt(buf[:, c, :], xx[:, c, :], vec[:, 5 + j, c:c + 1], hc[:, c, :], ALU.mult, ALU.add,
                          [("xx", c), ("xxe", 0), ("xxe", 1), "hh", "vec"], [(bn, c)])

            mk_xj(1, xrot, "xrot")
            yield
            i = proj8(lambda c: lw1[:, c, :], xrot, "xrot", ["lw1"])
            P.act(lwt[:], pp[i], AF.Tanh, [f"pp{i}"], ["lwt"])
            yield
            mk_xj(4, xrot, "xrot")
            yield
            i = proj8(lambda c: la1[:, c, :], xrot, "xrot", ["la1"])
            P.cp("scalar", lat[:], pp[i], [f"pp{i}"], ["lat"])
            yield
            mk_xj(5, xrot, "xrot")
            yield
            i = proj8(lambda c: g1[:, c, :], xrot, "xrot", ["g1"])
            P.act(sg[:], pp[i], AF.Sigmoid, [f"pp{i}"], ["sg"])
            yield
            mk_xj(0, xr, "xr")
            yield
            mk_xj(2, xk, "xk")
            yield
            mk_xj(3, xv, "xv")
            if ti + 1 < len(RW_ORDER1):
                load_hh(ti + 1)
            yield

        def prep(ti, oc, q, z):
            isctx, idx = RW_ORDER1[ti]
            tg = 16 if isctx else idx
            cs = slice(oc * 128, (oc + 1) * 128)
            t = f32t
            AR, KT, BT, Vb, gam = ARq[q], KTq[q], BTq[q], Vbq[z], gamq[z]
            i = proj8(lambda c: wr[:, c, cs], xr, "xr", ["rwkv_wr"])
            P.cp("scalar", t["r"][:], pp[i], [f"pp{i}"], ["r"])
            i = proj8(lambda c: wk[:, c, cs], xk, "xk", ["rwkv_wk"])
            P.cp("scalar", t["k"][:], pp[i], [f"pp{i}"], ["k"])
            i = proj8(lambda c: wv[:, c, cs], xv, "xv", ["rwkv_wv"])
            vt4 = VTbd[:].rearrange("p u (h s) -> p u h s", h=2)
            for h2 in range(2):
                sl = slice(h2 * 64, (h2 + 1) * 64)
                P.cp("scalar", vt4[sl, :, h2, :], v3(pp[i][sl, :]), [f"pp{i}"], ["VTbd"])
            i = nxt("pp")
            P.mm(pp[i], g2[:, cs], sg[:], True, True, ["g2", "sg"], [f"pp{i}"])
            gt4 = GTbd[:].rearrange("p u (h s) -> p u h s", h=2)
            for h2 in range(2):
                sl = slice(h2 * 64, (h2 + 1) * 64)
                P.cp("scalar", gt4[sl, :, h2, :], v3(pp[i][sl, :]), [f"pp{i}"], ["GTbd"])
            yield
            j = nxt("pb")
            for u in range(4):
                P.tr(pb[j][:, u * 128:(u + 1) * 128], VTbd[:, u, :], identf[:], ["VTbd", "identf"], [f"pb{j}"])
            pv = u128(pb[j][:])
            for h2 in range(2):
                sl = slice(h2 * 64, (h2 + 1) * 64)
                P.cp("scalar", Vf[sl, :, :], pv[sl, :, h2 * 64:(h2 + 1) * 64], [f"pb{j}"], ["Vf"])
            P.cp("gpsimd", Vb[:], Vf[:], ["Vf"], [f"Vb{z}"])
            P.dma("sync", S["vst"][tg, oc].rearrange("p (u s) -> p u s", s=64), Vf[:], reads=["Vf"], writes=[("vst", tg, oc)], sem="Vf")
            j = nxt("pb")
            for u in range(4):
                P.tr(pb[j][:, u * 128:(u + 1) * 128], GTbd[:, u, :], identf[:], ["GTbd", "identf"], [f"pb{j}"])
            pv = u128(pb[j][:])
            for h2 in range(2):
                sl = slice(h2 * 64, (h2 + 1) * 64)
                P.cp("scalar", Gf[sl, :, :], pv[sl, :, h2 * 64:(h2 + 1) * 64], [f"pb{j}"], ["Gf"])
            P.dma("sync", S["gst"][tg, oc].rearrange("p (u s) -> p u s", s=64), Gf[:], reads=["Gf"], writes=[("gst", tg, oc)], sem="Gf")
            yield
            for d in range(2):
                dl = slice(d * 64, (d + 1) * 64)
                i = nxt("pp")
                P.mm(pp[i], w2s[dl, cs], lwt[dl, :], True, True, ["w2s", "lwt"], [f"pp{i}"])
                P.act(t[f"sw{d}"][:], pp[i], AF.Sigmoid, [f"pp{i}", "vec"], [f"sw{d}"], bias=vec[:, 11 + d, oc:oc + 1])
                i = nxt("pp")
                P.mm(pp[i], a2s[dl, cs], lat[dl, :], True, True, ["a2s", "lat"], [f"pp{i}"])
                P.act(t[f"ag{d}"][:], pp[i], AF.Sigmoid, [f"pp{i}", "vec"], [f"ag{d}"], bias=vec[:, 13 + d, oc:oc + 1])
            yield
            P.ts("vector", t["kq"][:], t["k"][:], vec[:, 15, oc:oc + 1], None, ALU.mult, None, ["k", "vec"], ["kq"])
            P.act(sqb[:], t["kq"][:], AF.Square, ["kq"], ["sqb"])
            i = nxt("pp")
            P.mm(pp[i], bones[:], sqb[:], True, True, ["bones", "sqb"], [f"pp{i}"])
            P.act(t["lnv"][:], pp[i], AF.Ln, [f"pp{i}"], ["lnv"], bias=1e-12)
            P.act(t["rs"][:], t["lnv"][:], AF.Exp, ["lnv"], ["rs"], scale=-0.5)
            P.tt("gpsimd", t["kkn"][:], t["kq"][:], t["rs"][:], ALU.mult, ["kq", "rs"], ["kkn"])
            for d in range(2):
                sw, ag, kd, bb = t[f"sw{d}"], t[f"ag{d}"], t[f"kd{d}"], t[f"b{d}"]
                EE = "gpsimd" if d == 0 else "vector"
                P.ts(EE, t["fac"][:], ag[:], vec[:, 16, oc:oc + 1], vec[:, 17, oc:oc + 1], ALU.mult, ALU.add, [f"ag{d}", "vec"], ["fac"])
                P.tt(EE, kd[:], t["k"][:], t["fac"][:], ALU.mult, ["k", "fac"], [f"kd{d}"])
                P.tt(EE, bb[:], t["kkn"][:], ag[:], ALU.mult, ["kkn", f"ag{d}"], [f"b{d}"])
                P.op("vector", lambda e, sw=sw: e.tensor_tensor_scan(out=t["L"][:], data0=rmask[:], data1=sw[:], initial=0.0,
                                                                      op0=ALU.mult, op1=ALU.add), [f"sw{d}", "rmask"], ["L"])
                L3 = v3(t["L"][:])
                if d == 0:
                    P.tt(EE, t["Lx"][:], t["L"][:], sw[:], ALU.subtract, ["L", f"sw{d}"], ["Lx"])
                    Li, Lin = t["L"], "L"
                else:
                    P.tt(EE, v3(t["Lx"][:]), L3[:, :, 63:64].broadcast_to([128, 4, 64]), L3, ALU.subtract, ["L"], ["Lx"])
                    P.tt(EE, t["Lb"][:], t["Lx"][:], sw[:], ALU.add, ["Lx", f"sw{d}"], ["Lb"])
                    Li, Lin = t["Lb"], "Lb"
                P.act(t["E1"][:], Li[:], AF.Exp, [Lin], ["E1"], scale=-C0)
                P.act(t["E3"][:], Li[:], AF.Exp, [Lin], ["E3"], scale=C0)
                P.act(t["E2"][:], t["Lx"][:], AF.Exp, ["Lx"], ["E2"], scale=-C0)
                ar5 = AR[d][:].rearrange("p u a (h s) -> p u a h s", h=2)
                kt4 = KT[d][:].rearrange("p u (h s) -> p u h s", h=2)
                bt4 = BT[d][:].rearrange("p u (h s) -> p u h s", h=2)
                for h2 in range(2):
                    sl = slice(h2 * 64, (h2 + 1) * 64)
                    P.stt(ar5[sl, :, 0, h2, :], v3(t["kkn"][sl, :]), -1.0, v3(t["E2"][sl, :]), ALU.mult, ALU.mult, ["kkn", "E2"], [f"AR{q}{d}"])
                    P.tt(EE, ar5[sl, :, 1, h2, :], v3(t["r"][sl, :]), v3(t["E1"][sl, :]), ALU.mult, ["r", "E1"], [f"AR{q}{d}"])
                    P.tt(EE, kt4[sl, :, h2, :], v3(kd[sl, :]), v3(t["E3"][sl, :]), ALU.mult, [f"kd{d}", "E3"], [f"KT{q}{d}"])
                    P.tt(EE, bt4[sl, :, h2, :], v3(bb[sl, :]), v3(t["E3"][sl, :]), ALU.mult, [f"b{d}", "E3"], [f"BT{q}{d}"])
                E13 = v3(t["E1"][:])
                gsrc = E13[:, :, 63] if d == 0 else E13[:, :, 0]
                P.cp("vector", gam[d][:], gsrc, ["E1"], [f"gam{z}{d}"])
                if d == 1:
                    P.cp("gpsimd", gamb_t[:, oc, :], gam[1][:], [f"gam{z}1"], ["gamb_t"])
                yield
            P.tt("gpsimd", t["ks"][:], t["kd0"][:], t["kd1"][:], ALU.add, ["kd0", "kd1"], ["ks"])
            for h2 in range(2):
                sl = slice(h2 * 64, (h2 + 1) * 64)
                P.stt(RK[sl, :, h2, :], v3(t["r"][sl, :]), vec[sl, 18, oc:oc + 1], v3(t["ks"][sl, :]), ALU.mult, ALU.mult, ["r", "ks", "vec"], ["RK"])
            i = nxt("pp")
            for u in range(4):
                P.mm(pp[i][:, u:u + 1], RK[:, u, :, :].rearrange("p h s -> p (h s)"), onesb[:, 0:1], True, True, ["RK", "onesb"], [f"pp{i}"])
            P.cp("scalar", bon_t[:, oc, :], pp[i][:, 0:4], [f"pp{i}"], ["bon_t"])
            if oc == 7:
                P.dma("sync", S["gamb"][tg], gamb_t[:].rearrange("p a b -> p (a b)"), reads=["gamb_t"], writes=[("gamb", tg)], sem="gamb_t")
                P.dma("sync", S["bon"][tg], bon_t[:].rearrange("p a b -> p (a b)"), reads=["bon_t"], writes=[("bon", tg)], sem="bon_t")
            yield

        def chain(ti, oc, q, d, z):
            AR, KT, BT, Vb = ARq[q][d], KTq[q][d], BTq[q][d], Vbq[z]
            ARn, KTn, BTn, Vbn = f"AR{q}{d}", f"KT{q}{d}", f"BT{q}{d}", f"Vb{z}"
            iv, fn = inv[d], fin[q][d]
            IR = lambda nm: f"i{d}_{nm}"
            FR = lambda nm: f"f{q}{d}_{nm}"
            mS, mC = (0, 2) if d == 0 else (2, 0)
            mSI = masks[:, mS:mS + 2, :].rearrange("p a b -> p (a b)").unsqueeze(1).broadcast_to([128, 4, 256])
            mCb = masks[:, mC, :].unsqueeze(1).broadcast_to([128, 4, 128])
            idb = identb[:].unsqueeze(1).broadcast_to([128, 4, 128])
            for src, srcn, dst, dstn in ((AR[:, :, 0, :], ARn, iv["Atok"], IR("Atok")), (BT[:], BTn, iv["Btok"], IR("Btok")),
                                         (KT[:], KTn, fn["Ktok"], FR("Ktok"))):
                j = nxt("pb")
                pbt = pb[j][:].bitcast(BF16)
                for u in range(4):
                    P.tr(pbt[:, u * 128:(u + 1) * 128], src[:, u, :], identb[:], [srcn, "identb"], [f"pb{j}"])
                P.cp("scalar", dst[:].rearrange("p u x -> p (u x)"), pbt[:, 0:512], [f"pb{j}"], [dstn])
            mSb = masks[:, mS, :].unsqueeze(1).broadcast_to([128, 4, 128])
            mIb = masks[:, mS + 1, :].unsqueeze(1).broadcast_to([128, 4, 128])

            def two_bank(mm_fn):
                j0, j1 = nxt("pb"), nxt("pb")
                for u in range(4):
                    mm_fn(u, pb[j0][:, u * 128:(u + 1) * 128], f"pb{j0}", pb[j1][:, u * 128:(u + 1) * 128], f"pb{j1}")
                return j0, j1

            for lhs, lhsn, dst, dstn in ((BT, BTn, iv["MQ"], IR("MQ")), (KT, KTn, fn["NP"], FR("NP"))):
                def mm_ab(u, o0, n0, o1, n1, lhs=lhs, lhsn=lhsn):
                    P.mm(o0, lhs[:, u, :], AR[:, u, 0, :], True, True, [lhsn, ARn], [n0])
                    P.mm(o1, lhs[:, u, :], AR[:, u, 1, :], True, True, [lhsn, ARn], [n1])
                j0, j1 = two_bank(mm_ab)
                P.tt("vector", dst[:, :, 0:128], u128(pb[j0][:]), mSb, ALU.mult, [f"pb{j0}", "masks"], [dstn])
                P.tt("vector", dst[:, :, 128:256], u128(pb[j1][:]), mIb, ALU.mult, [f"pb{j1}", "masks"], [dstn])
            j = nxt("pb")
            for u in range(4):
                P.mm(pb[j][:, u * 128:(u + 1) * 128], AR[:, u, 0, :], BT[:, u, :], True, True, [ARn, BTn], [f"pb{j}"])
            cur, curn, nx, nxn = iv["MWa"], IR("MWa"), iv["MWb"], IR("MWb")
            P.tt("vector", cur[:, :, 0, :], u128(pb[j][:]), mCb, ALU.mult, [f"pb{j}", "masks"], [curn])
            yield
            j = nxt("pb")
            for u in range(4):
                P.mm(pb[j][:, u * 128:(u + 1) * 128], iv["MQ"][:, u, 0:128], cur[:, u, 0, :], True, True, [IR("MQ"), curn], [f"pb{j}"])
            P.cp("scalar", nx[:, :, 0, :], u128(pb[j][:]), [f"pb{j}"], [nxn])
            P.tt("gpsimd", nx[:, :, 1, :], cur[:, :, 0, :], idb, ALU.add, [curn, "identb"], [nxn])
            j = nxt("pb")
            for u in range(4):
                P.mm(pb[j][:, u * 128:(u + 1) * 128], cur[:, u, 0, :], iv["MQ"][:, u, 0:128], True, True, [IR("MQ"), curn], [f"pb{j}"])
            curT, curTn, nxT, nxTn = iv["MTa"], IR("MTa"), iv["MTb"], IR("MTb")
            P.cp("scalar", curT[:], u128(pb[j][:]), [f"pb{j}"], [curTn])
            cur, curn, nx, nxn = nx, nxn, cur, curn
            yield
            for lev in range(1, 5):
                def mm_lev(u, o0, n0, o1, n1, cur=cur, curn=curn, curT=curT, curTn=curTn):
                    P.mm(o0, curT[:, u, :], cur[:, u, 0, :], True, True, [curTn, curn], [n0])
                    P.mm(o1, curT[:, u, :], cur[:, u, 1, :], True, True, [curTn, curn], [n1])
                j0, j1 = two_bank(mm_lev)
                P.cp("scalar", nx[:, :, 0, :], u128(pb[j0][:]), [f"pb{j0}"], [nxn])
                P.tt("vector", nx[:, :, 1, :], u128(pb[j1][:]), cur[:, :, 1, :], ALU.add, [f"pb{j1}", curn], [nxn])
                j = nxt("pb")
                for u in range(4):
                    P.mm(pb[j][:, u * 128:(u + 1) * 128], cur[:, u, 0, :], curT[:, u, :], True, True, [curn, curTn], [f"pb{j}"])
                P.cp("scalar", nxT[:], u128(pb[j][:]), [f"pb{j}"], [nxTn])
                cur, curn, nx, nxn = nx, nxn, cur, curn
                curT, curTn, nxT, nxTn = nxT, nxTn, curT, curTn
                yield
            j = nxt("pb")
            for u in range(4):
                P.mm(pb[j][:, u * 128:(u + 1) * 128], curT[:, u, :], cur[:, u, 1, :], True, True, [curTn, curn], [f"pb{j}"])
            P.tt("vector", nx[:, :, 1, :], u128(pb[j][:]), cur[:, :, 1, :], ALU.add, [f"pb{j}", curn], [nxn])
            W6, W6n = nx, nxn
            j = nxt("pb")
            for u in range(4):
                P.mm(pb[j][:, u * 64:(u + 1) * 64], fn["NP"][:, u, 0:128], Vb[:, u, :], True, True, [FR("NP"), Vbn], [f"pb{j}"])
            P.cp("scalar", fn["NVb"][:].rearrange("p u x -> p (u x)"), pb[j][:, 0:256], [f"pb{j}"], [FR("NVb")])
            yield

            def mm_d(u, o0, n0, o1, n1):
                P.mm(o0, W6[:, u, 1, :], iv["MQ"][:, u, 128:256], True, True, [W6n, IR("MQ")], [n0])
                P.mm(o1, W6[:, u, 1, :], iv["Btok"][:, u, :], True, True, [W6n, IR("Btok")], [n1])
            j0, j1 = two_bank(mm_d)
            P.cp("scalar", fn["XW"][:, :, 0:128], u128(pb[j0][:]), [f"pb{j0}"], [FR("XW")])
            P.cp("vector", fn["XW"][:, :, 128:256], u128(pb[j1][:]), [f"pb{j1}"], [FR("XW")])
            yield

            def mm_f(u, o0, n0, o1, n1):
                P.mm(o0, iv["Atok"][:, u, :], fn["XW"][:, u, 0:128], True, True, [IR("Atok"), FR("XW")], [n0])
                P.mm(o1, iv["Atok"][:, u, :], fn["XW"][:, u, 128:256], True, True, [IR("Atok"), FR("XW")], [n1])
            j0, j1 = two_bank(mm_f)
            P.tt("vector", fn["GY"][:], u128(pb[j0][:]), AR[:, :, 1, :], ALU.add, [f"pb{j0}", ARn], [FR("GY")])
            P.tt("vector", fn["GS"][:], u128(pb[j1][:]), idb, ALU.add, [f"pb{j1}", "identb"], [FR("GS")])
            yield

        def finish(ti, oc, q, z):
            isctx, idx = RW_ORDER1[ti]
            tg = 16 if isctx else idx
            sf, sb_ = fin[q]
            F0 = lambda nm: f"f{q}0_{nm}"
            F1 = lambda nm: f"f{q}1_{nm}"
            Vb, Vbn, gam = Vbq[z], f"Vb{z}", gamq[z]
            SFR = ("Sf", oc)
            for u in range(4):
                yo = pf[:, u * 64:(u + 1) * 64]
                P.mm(yo, sf["NP"][:, u, 128:256], Vb[:, u, :], True, False, [F0("NP"), Vbn], ["pf"])
                P.mm(yo, sf["XW"][:, u, 0:128], sf["NVb"][:, u, :], False, False, [F0("XW"), F0("NVb")], ["pf"])
                P.mm(yo, sb_["NP"][:, u, 128:256], Vb[:, u, :], False, False, [F1("NP"), Vbn], ["pf"])
                P.mm(yo, sb_["XW"][:, u, 0:128], sb_["NVb"][:, u, :], False, False, [F1("XW"), F1("NVb")], ["pf"])
                P.mm(yo, sf["GY"][:, u, :], Sf[:, oc, :], False, True, [F0("GY"), SFR], ["pf"])
                so = pf[:, 256:320]
                P.mm(so, sf["Ktok"][:, u, :], Vb[:, u, :], True, False, [F0("Ktok"), Vbn], ["pf"])
                P.mm(so, sf["XW"][:, u, 128:256], sf["NVb"][:, u, :], False, False, [F0("XW"), F0("NVb")], ["pf"])
                P.mm(so, sf["GS"][:, u, :], Sf[:, oc, :], False, True, [F0("GS"), SFR], ["pf"])
                P.ts("vector", Sf[:, oc, :], so, gam[0][:, u:u + 1], None, ALU.mult, None, ["pf", f"gam{z}0"], [SFR])
                yield
            P.cp("vector", YPs[:].rearrange("p u x -> p (u x)"), pf[:, 0:256], ["pf"], ["YPs"])
            P.dma("sync", S["yp"][tg, oc], YPs[:].rearrange("p u x -> p (u x)"), reads=["YPs"], writes=[("yp", tg, oc)], sem="YPs")
            j = nxt("pb")
            for u in range(4):
                so = pb[j][:, u * 64:(u + 1) * 64]
                P.mm(so, sb_["Ktok"][:, u, :], Vb[:, u, :], True, False, [F1("Ktok"), Vbn], [f"pb{j}"])
                P.mm(so, sb_["XW"][:, u, 128:256], sb_["NVb"][:, u, :], False, True, [F1("XW"), F1("NVb")], [f"pb{j}"])
            P.cp("scalar", SAs[:].rearrange("p u x -> p (u x)"), pb[j][:, 0:256], [f"pb{j}"], ["SAs"])
            P.dma("sync", S["sadd"][tg, oc], SAs[:].rearrange("p u x -> p (u x)"), reads=["SAs"], writes=[("sadd", tg, oc)], sem="SAs")
            P.dma("sync", S["gyb"][tg, oc], sb_["GY"][:].rearrange("p u x -> p (u x)"), reads=[F1("GY")], writes=[("gyb", tg, oc)], sem=F1("GY"))
            P.dma("sync", S["gsb"][tg, oc], sb_["GS"][:].rearrange("p u x -> p (u x)"), reads=[F1("GS")], writes=[("gsb", tg, oc)], sem=F1("GS"))
            yield

        NT = len(RW_ORDER1)
        NJ = NT * 8
        done = {"prep": set(), "c0": set(), "c1": set(), "fin": set(), "tprep": set()}

        def stream_P():
            for ti in range(NT):
                yield ("tprep", ti, lambda ti=ti: (ti == 0 or ("prep", (ti - 1) * 8 + 7) in donef), lambda ti=ti: tprep(ti))
                for oc in range(8):
                    k = ti * 8 + oc
                    yield ("prep", k, lambda k=k: ((k < 2 or (("c0", k - 2) in donef and ("c1", k - 2) in donef)) and (k < 3 or ("fin", k - 3) in donef)),
                           lambda ti=ti, oc=oc, k=k: prep(ti, oc, k % 2, k % 3))

        def stream_C(d):
            for k in range(NJ):
                ti, oc = divmod(k, 8)
                yield (f"c{d}", k, lambda k=k: (("prep", k) in donef and (k < 2 or ("fin", k - 2) in donef)),
                       lambda ti=ti, oc=oc, k=k: chain(ti, oc, k % 2, d, k % 3))

        def stream_F():
            for k in range(NJ):
                ti, oc = divmod(k, 8)
                yield ("fin", k, lambda k=k: (("c0", k) in donef and ("c1", k) in donef),
                       lambda ti=ti, oc=oc, k=k: finish(ti, oc, k % 2, k % 3))

        donef = set()
        load_hh(0)
        streams = [stream_C(0), stream_C(1), stream_F(), stream_P()]
        cur = [None] * 4
        pend = [None] * 4
        alive = [True] * 4
        while any(alive):
            progressed = False
            for si in range(4):
                if not alive[si]:
                    continue
                if cur[si] is None:
                    if pend[si] is None:
                        try:
                            pend[si] = next(streams[si])
                        except StopIteration:
                            alive[si] = False
                            continue
                    kind, k, ready, mk = pend[si]
                    if not ready():
                        continue
                    cur[si] = (kind, k, mk())
                    pend[si] = None
                kind, k, gen = cur[si]
                try:
                    next(gen)
                    progressed = True
                except StopIteration:
                    donef.add((kind, k))
                    cur[si] = None
                    progressed = True
            assert progressed or not any(alive), "scheduler stuck"


def stage_rwkv1a(P, io, G, hp, S):
    vec, masks, identb, identf, bones, onesb, rmask = (G[k] for k in ("vec", "masks", "identb", "identf", "bones", "onesb", "rmask"))
    with P.phase("rwkv1a"):
        wr = P.sb([128, 8, 1024], BF16)
        wk = P.sb([128, 8, 1024], BF16)
        wv = P.sb([128, 8, 1024], BF16)
        for w, nm in ((wr, "rwkv_wr"), (wk, "rwkv_wk"), (wv, "rwkv_wv")):
            P.dma("gpsimd", w[:], fm(io[nm]), writes=[nm], sem=nm)
        lw1 = P.sb([128, 8, 128], BF16)
        la1 = P.sb([128, 8, 128], BF16)
        g1 = P.sb([128, 8, 128], BF16)
        for d in range(2):
            P.dma("gpsimd", lw1[:, :, d * 64:(d + 1) * 64], io["rwkv_w1"][d].rearrange("(c p) j -> p c j", p=128), writes=["lw1"], sem=f"lw1{d}")
            P.dma("gpsimd", la1[:, :, d * 64:(d + 1) * 64], io["rwkv_a1"][d].rearrange("(c p) j -> p c j", p=128), writes=["la1"], sem=f"la1{d}")
        P.dma("gpsimd", g1[:], io["rwkv_g1"].rearrange("(c p) j -> p c j", p=128), writes=["g1"], sem="g1")
        w2s = P.sb([128, 1024], BF16)
        a2s = P.sb([128, 1024], BF16)
        g2 = P.sb([128, 1024], BF16)
        P.dma("gpsimd", w2s[:], io["rwkv_w2"].rearrange("d j f -> (d j) f"), writes=["w2s"], sem="w2s")
        P.dma("gpsimd", a2s[:], io["rwkv_a2"].rearrange("d j f -> (d j) f"), writes=["a2s"], sem="a2s")
        P.dma("gpsimd", g2[:], io["rwkv_g2"], writes=["g2"], sem="g2")

        hh = P.sb([128, 8, 384], F32)
        xx = P.sb([128, 8, 256], F32)
        xr = P.sb([128, 8, 256], BF16)
        xk = P.sb([128, 8, 256], BF16)
        xv = P.sb([128, 8, 256], BF16)
        xrot = P.sb([128, 8, 256], BF16)
        lwt = P.sb([128, 256], BF16)
        lat = P.sb([128, 256], BF16)
        sg = P.sb([128, 256], BF16)
        NSET = 3
        bufs = []
        for w_ in range(NSET):
            B_ = {"t": {}}
            for nm in ("r", "k", "sw0", "sw1", "ag0", "ag1", "kq", "lnv", "rs", "kkn", "fac", "kd0", "kd1", "b0", "b1",
                       "L", "Lx", "Lb", "E1", "E2", "E3", "ks"):
                B_["t"][nm] = P.sb([128, 256], F32, f"t{w_}_" + nm)
            B_["sqb"] = P.sb([128, 256], BF16)
            B_["RK"] = P.sb([128, 4, 2, 64], BF16)
            B_["VTbd"] = P.sb([128, 4, 128], F32)
            B_["GTbd"] = P.sb([128, 4, 128], F32)
            B_["Vf"] = P.sb([128, 4, 64], F32)
            B_["Gf"] = P.sb([128, 4, 64], F32)
            B_["ops"] = P.sb([128, 2, 4, 256], BF16)
            B_["vb"] = P.sb([128, 4, 64], BF16)
            B_["gam"] = P.sb([128, 2, 4], F32)
            bufs.append(B_)
            P.memset("gpsimd", B_["RK"][:], 0.0, [("RK", w_)])
            P.memset("gpsimd", B_["VTbd"][:], 0.0, [("VTbd", w_)])
            P.memset("gpsimd", B_["GTbd"][:], 0.0, [("GTbd", w_)])
        P.ns_set = frozenset(["r", "k", "sw0", "sw1", "ag0", "ag1", "kq", "lnv", "rs", "kkn", "fac", "kd0", "kd1", "b0", "b1",
                              "L", "Lx", "Lb", "E1", "E2", "E3", "ks", "sqb", "RK", "VTbd", "GTbd", "Vf", "Gf", "ops_st", "vb_st", "gam_st"])
        gamb_t = P.sb([128, 8, 4], F32)
        bon_t = P.sb([128, 8, 4], F32)
        ppt = [P.ps([128, 512], F32) for _ in range(4)]
        pp = [t_[:, 0:256] for t_ in ppt]
        pb = [P.ps([128, 512], F32) for _ in range(4)]
        cnt = {"pp": 0, "pb": 0}
        nmod = {"pp": 4, "pb": 4}

        def nxt(kind):
            i = cnt[kind] % nmod[kind]
            cnt[kind] += 1
            return i

        def v3(ap):
            return ap.rearrange("p (u s) -> p u s", s=64)

        def u128(ap):
            return ap.rearrange("p (u x) -> p u x", x=128)

        def load_hh(ti):
            isctx, idx = RW_ORDER1[ti]
            off = 4288 if isctx else 64 + 256 * idx
            P.dma("sync", hh[:], fm(hp[:, off - 64: off + 320]), writes=["hh"], sem="hh")

        def proj8(w_cols_fn, xb, bn, extra_r):
            i = nxt("pp")
            for c in range(8):
                P.mm(pp[i], w_cols_fn(c), xb[:, c, :], c == 0, c == 7, [(bn, c)] + extra_r, [f"pp{i}"])
            return i

        def tprep(ti):
            isctx, idx = RW_ORDER1[ti]
            hc = hh[:, :, 64:320]
            XXW = [("xx", c) for c in range(8)]
            if not isctx:
                h4 = hh[:, :, 64:320].rearrange("p c (r w) -> p c r w", w=64)
                x4 = xx[:].rearrange("p c (r w) -> p c r w", w=64)
                P.tt("vector", x4[:, 0:2, :, 1:64], h4[:, 0:2, :, 0:63], h4[:, 0:2, :, 1:64], ALU.subtract, ["hh"], XXW[0:2])
                P.ts("gpsimd", x4[:, 0:2, :, 0:1], h4[:, 0:2, :, 0:1], -1.0, 0.0, ALU.mult, ALU.add, ["hh"], [("xxe", 0)])
                P.tt("vector", x4[:, 2:4, :, 0:63], h4[:, 2:4, :, 1:64], h4[:, 2:4, :, 0:63], ALU.subtract, ["hh"], XXW[2:4])
                P.ts("gpsimd", x4[:, 2:4, :, 63:64], h4[:, 2:4, :, 63:64], -1.0, 0.0, ALU.mult, ALU.add, ["hh"], [("xxe", 1)])
                P.tt("gpsimd", xx[:, 4:6, :], hh[:, 4:6, 0:256], hh[:, 4:6, 64:320], ALU.subtract, ["hh"], XXW[4:6])
                P.tt("gpsimd", xx[:, 6:8, :], hh[:, 6:8, 128:384], hh[:, 6:8, 64:320], ALU.subtract, ["hh"], XXW[6:8])
            else:
                P.tt("vector", xx[:, 0:4, :], hh[:, 0:4, 63:319], hh[:, 0:4, 64:320], ALU.subtract, ["hh"], XXW[0:4] + [("xxe", 0)])
                P.tt("gpsimd", xx[:, 4:8, :], hh[:, 4:8, 65:321], hh[:, 4:8, 64:320], ALU.subtract, ["hh"], XXW[4:8] + [("xxe", 1)])
            yield

            def mk_xj(j, buf, bn):
                for c in range(8):
                    P.stt(buf[:, c, :], xx[:, c, :], vec[:, 5 + j, c:c + 1], hc[:, c, :], ALU.mult, ALU.add,
                          [("xx", c), ("xxe", 0), ("xxe", 1), "hh", "vec"], [(bn, c)])

            mk_xj(1, xrot, "xrot")
            yield
            i = proj8(lambda c: lw1[:, c, :], xrot, "xrot", ["lw1"])
            P.act(lwt[:], pp[i], AF.Tanh, [f"pp{i}"], ["lwt"])
            yield
            mk_xj(4, xrot, "xrot")
            yield
            i = proj8(lambda c: la1[:, c, :], xrot, "xrot", ["la1"])
            P.cp("scalar", lat[:], pp[i], [f"pp{i}"], ["lat"])
            yield
            mk_xj(5, xrot, "xrot")
            yield
            i = proj8(lambda c: g1[:, c, :], xrot, "xrot", ["g1"])
            P.act(sg[:], pp[i], AF.Sigmoid, [f"pp{i}"], ["sg"])
            yield
            mk_xj(0, xr, "xr")
            yield
            mk_xj(2, xk, "xk")
            yield
            mk_xj(3, xv, "xv")
            if ti + 1 < len(RW_ORDER1):
                load_hh(ti + 1)
            yield

        def prep(ti, oc, w):
            isctx, idx = RW_ORDER1[ti]
            tg = 16 if isctx else idx
            cs = slice(oc * 128, (oc + 1) * 128)
            B_ = bufs[w]
            t, sqb, RK, VTbd, GTbd, Vf, Gf = B_["t"], B_["sqb"], B_["RK"], B_["VTbd"], B_["GTbd"], B_["Vf"], B_["Gf"]
            ops_st, vb_st, gam_st = B_["ops"], B_["vb"], B_["gam"]
            i = proj8(lambda c: wr[:, c, cs], xr, "xr", ["rwkv_wr"])
            P.cp("scalar", t["r"][:], pp[i], [f"pp{i}"], ["r"])
            i = proj8(lambda c: wk[:, c, cs], xk, "xk", ["rwkv_wk"])
            P.cp("scalar", t["k"][:], pp[i], [f"pp{i}"], ["k"])
            i = proj8(lambda c: wv[:, c, cs], xv, "xv", ["rwkv_wv"])
            vt4 = VTbd[:].rearrange("p u (h s) -> p u h s", h=2)
            for h2 in range(2):
                sl = slice(h2 * 64, (h2 + 1) * 64)
                P.cp("scalar", vt4[sl, :, h2, :], v3(pp[i][sl, :]), [f"pp{i}"], ["VTbd"])
            i = nxt("pp")
            P.mm(pp[i], g2[:, cs], sg[:], True, True, ["g2", "sg"], [f"pp{i}"])
            gt4 = GTbd[:].rearrange("p u (h s) -> p u h s", h=2)
            for h2 in range(2):
                sl = slice(h2 * 64, (h2 + 1) * 64)
                P.cp("scalar", gt4[sl, :, h2, :], v3(pp[i][sl, :]), [f"pp{i}"], ["GTbd"])
            yield
            j = nxt("pb")
            for u in range(4):
                P.tr(pb[j][:, u * 128:(u + 1) * 128], VTbd[:, u, :], identf[:], ["VTbd", "identf"], [f"pb{j}"])
            pv = u128(pb[j][:])
            for h2 in range(2):
                sl = slice(h2 * 64, (h2 + 1) * 64)
                P.cp("scalar", Vf[sl, :, :], pv[sl, :, h2 * 64:(h2 + 1) * 64], [f"pb{j}"], ["Vf"])
            P.cp("gpsimd", vb_st[:], Vf[:], ["Vf"], ["vb_st"])
            P.dma("sync", S["vst"][tg, oc].rearrange("p (u s) -> p u s", s=64), Vf[:], reads=["Vf"], writes=[("vst", tg, oc)], sem="Vf")
            j = nxt("pb")
            for u in range(4):
                P.tr(pb[j][:, u * 128:(u + 1) * 128], GTbd[:, u, :], identf[:], ["GTbd", "identf"], [f"pb{j}"])
            pv = u128(pb[j][:])
            for h2 in range(2):
                sl = slice(h2 * 64, (h2 + 1) * 64)
                P.cp("scalar", Gf[sl, :, :], pv[sl, :, h2 * 64:(h2 + 1) * 64], [f"pb{j}"], ["Gf"])
            P.dma("sync", S["gst"][tg, oc].rearrange("p (u s) -> p u s", s=64), Gf[:], reads=["Gf"], writes=[("gst", tg, oc)], sem="Gf")
            yield
            for d in range(2):
                dl = slice(d * 64, (d + 1) * 64)
                i = nxt("pp")
                P.mm(pp[i], w2s[dl, cs], lwt[dl, :], True, True, ["w2s", "lwt"], [f"pp{i}"])
                P.act(t[f"sw{d}"][:], pp[i], AF.Sigmoid, [f"pp{i}", "vec"], [f"sw{d}"], bias=vec[:, 11 + d, oc:oc + 1])
                i = nxt("pp")
                P.mm(pp[i], a2s[dl, cs], lat[dl, :], True, True, ["a2s", "lat"], [f"pp{i}"])
                P.act(t[f"ag{d}"][:], pp[i], AF.Sigmoid, [f"pp{i}", "vec"], [f"ag{d}"], bias=vec[:, 13 + d, oc:oc + 1])
            yield
            P.ts("vector", t["kq"][:], t["k"][:], vec[:, 15, oc:oc + 1], None, ALU.mult, None, ["k", "vec"], ["kq"])
            P.act(sqb[:], t["kq"][:], AF.Square, ["kq"], ["sqb"])
            i = nxt("pp")
            P.mm(pp[i], bones[:], sqb[:], True, True, ["bones", "sqb"], [f"pp{i}"])
            P.act(t["lnv"][:], pp[i], AF.Ln, [f"pp{i}"], ["lnv"], bias=1e-12)
            P.act(t["rs"][:], t["lnv"][:], AF.Exp, ["lnv"], ["rs"], scale=-0.5)
            P.tt("vector", t["kkn"][:], t["kq"][:], t["rs"][:], ALU.mult, ["kq", "rs"], ["kkn"])
            for d in range(2):
                sw, ag, kd, bb = t[f"sw{d}"], t[f"ag{d}"], t[f"kd{d}"], t[f"b{d}"]
                EE = "vector"
                P.ts(EE, t["fac"][:], ag[:], vec[:, 16, oc:oc + 1], vec[:, 17, oc:oc + 1], ALU.mult, ALU.add, [f"ag{d}", "vec"], ["fac"])
                P.tt(EE, kd[:], t["k"][:], t["fac"][:], ALU.mult, ["k", "fac"], [f"kd{d}"])
                P.tt(EE, bb[:], t["kkn"][:], ag[:], ALU.mult, ["kkn", f"ag{d}"], [f"b{d}"])
                P.op("vector", lambda e, sw=sw: e.tensor_tensor_scan(out=t["L"][:], data0=rmask[:], data1=sw[:], initial=0.0,
                                                                      op0=ALU.mult, op1=ALU.add), [f"sw{d}", "rmask"], ["L"])
                L3 = v3(t["L"][:])
                if d == 0:
                    P.tt(EE, t["Lx"][:], t["L"][:], sw[:], ALU.subtract, ["L", f"sw{d}"], ["Lx"])
                    Li, Lin = t["L"], "L"
                else:
                    P.tt(EE, v3(t["Lx"][:]), L3[:, :, 63:64].broadcast_to([128, 4, 64]), L3, ALU.subtract, ["L"], ["Lx"])
                    P.tt(EE, t["Lb"][:], t["Lx"][:], sw[:], ALU.add, ["Lx", f"sw{d}"], ["Lb"])
                    Li, Lin = t["Lb"], "Lb"
                P.act(t["E1"][:], Li[:], AF.Exp, [Lin], ["E1"], scale=-C0)
                P.act(t["E3"][:], Li[:], AF.Exp, [Lin], ["E3"], scale=C0)
                P.act(t["E2"][:], t["Lx"][:], AF.Exp, ["Lx"], ["E2"], scale=-C0)
                P.stt(ops_st[:, d, 0, :], t["kkn"][:], -1.0, t["E2"][:], ALU.mult, ALU.mult, ["kkn", "E2"], ["ops_st"])
                P.tt("vector", ops_st[:, d, 1, :], t["r"][:], t["E1"][:], ALU.mult, ["r", "E1"], ["ops_st"])
                P.tt(EE, ops_st[:, d, 2, :], kd[:], t["E3"][:], ALU.mult, [f"kd{d}", "E3"], ["ops_st"])
                P.tt(EE, ops_st[:, d, 3, :], bb[:], t["E3"][:], ALU.mult, [f"b{d}", "E3"], ["ops_st"])
                E13 = v3(t["E1"][:])
                gsrc = E13[:, :, 63] if d == 0 else E13[:, :, 0]
                P.cp("vector", gam_st[:, d, :], gsrc, ["E1"], ["gam_st"])
                if d == 1:
                    P.cp("gpsimd", gamb_t[:, oc, :], gam_st[:, 1, :], ["gam_st"], ["gamb_t"])
                yield
            P.tt("vector", t["ks"][:], t["kd0"][:], t["kd1"][:], ALU.add, ["kd0", "kd1"], ["ks"])
            for h2 in range(2):
                sl = slice(h2 * 64, (h2 + 1) * 64)
                P.stt(RK[sl, :, h2, :], v3(t["r"][sl, :]), vec[sl, 18, oc:oc + 1], v3(t["ks"][sl, :]), ALU.mult, ALU.mult, ["r", "ks", "vec"], ["RK"])
            i = nxt("pp")
            for u in range(4):
                P.mm(pp[i][:, u:u + 1], RK[:, u, :, :].rearrange("p h s -> p (h s)"), onesb[:, 0:1], True, True, ["RK", "onesb"], [f"pp{i}"])
            P.cp("scalar", bon_t[:, oc, :], pp[i][:, 0:4], [f"pp{i}"], ["bon_t"])
            P.dma("sync", S["ops"][tg, oc], ops_st[:].rearrange("p d x n -> p (d x n)"), reads=["ops_st"], writes=[("ops", tg, oc)], sem="ops_st")
            P.dma("sync", S["vb"][tg, oc], vb_st[:].rearrange("p u s -> p (u s)"), reads=["vb_st"], writes=[("vb", tg, oc)], sem="vb_st")
            P.dma("sync", S["gam"][tg, oc], gam_st[:].rearrange("p d u -> p (d u)"), reads=["gam_st"], writes=[("gam", tg, oc)], sem="gam_st")
            yield


        NT = len(RW_ORDER1)
        load_hh(0)
        for ti in range(NT):
            isctx, idx = RW_ORDER1[ti]
            tg = 16 if isctx else idx
            for _ in tprep(ti):
                pass
            jobs = [(oc % NSET, prep(ti, oc, oc % NSET)) for oc in range(8)]
            active = []
            since = 99
            while jobs or active:
                if jobs and len(active) < NSET and (since >= 3 or not active):
                    active.append(jobs.pop(0))
                    since = 0
                since += 1
                for item in list(active):
                    P.ns = item[0]
                    try:
                        next(item[1])
                    except StopIteration:
                        active.remove(item)
                    P.ns = None
            P.dma("sync", S["gamb"][tg], gamb_t[:].rearrange("p a b -> p (a b)"), reads=["gamb_t"], writes=[("gamb", tg)], sem="gamb_t")
            P.dma("sync", S["bon"][tg], bon_t[:].rearrange("p a b -> p (a b)"), reads=["bon_t"], writes=[("bon", tg)], sem="bon_t")
        P.ns_set = frozenset()


def stage_rwkv1b(P, io, G, S):
    vec, masks, identb, identf, bones, onesb, rmask = (G[k] for k in ("vec", "masks", "identb", "identf", "bones", "onesb", "rmask"))
    with P.phase("rwkv1b"):
        YPs = P.sb([128, 4, 64], F32)
        SAs = P.sb([128, 4, 64], F32)
        Sf = P.sb([128, 8, 64], BF16)
        ARq = [[P.sb([128, 4, 2, 128], BF16, f"AR{q}{d}") for d in range(2)] for q in range(3)]
        KTq = [[P.sb([128, 4, 128], BF16, f"KT{q}{d}") for d in range(2)] for q in range(3)]
        BTq = [[P.sb([128, 4, 128], BF16, f"BT{q}{d}") for d in range(2)] for q in range(3)]
        stg = [P.sb([128, 2, 4, 256], BF16, f"stg{q}") for q in range(3)]
        Vbq = [P.sb([128, 4, 64], BF16, f"Vb{q}") for q in range(4)]
        gamq = [P.sb([128, 2, 4], F32, f"gam{q}") for q in range(4)]
        inv2 = []
        for q in range(2):
            row = []
            for d in range(2):
                st = {}
                for nm, shp in (("Atok", [128, 4, 128]), ("Btok", [128, 4, 128]), ("MQ", [128, 4, 256]), ("MWa", [128, 4, 2, 128]),
                                ("MWb", [128, 4, 2, 128]), ("MTa", [128, 4, 128]), ("MTb", [128, 4, 128])):
                    st[nm] = P.sb(shp, BF16, f"i{q}{d}_{nm}")
                row.append(st)
            inv2.append(row)
        fin = []
        for q in range(2):
            row = []
            for d in range(2):
                st = {}
                for nm, shp in (("Ktok", [128, 4, 128]), ("NP", [128, 4, 256]), ("XW", [128, 4, 256]), ("NVb", [128, 4, 64]),
                                ("GY", [128, 4, 128]), ("GS", [128, 4, 128])):
                    st[nm] = P.sb(shp, BF16, f"f{q}{d}_{nm}")
                row.append(st)
            fin.append(row)
        pf = P.ps([128, 512], F32)
        pb = [P.ps([128, 512], F32) for _ in range(7)]
        cnt = {"pb": 0}
        nmod = {"pb": 7}

        def nxt(kind):
            i = cnt[kind] % nmod[kind]
            cnt[kind] += 1
            return i

        for q in range(3):
            for d in range(2):
                P.memset("gpsimd", ARq[q][d][:], 0.0, [f"AR{q}{d}"])
                P.memset("gpsimd", KTq[q][d][:], 0.0, [f"KT{q}{d}"])
                P.memset("gpsimd", BTq[q][d][:], 0.0, [f"BT{q}{d}"])
        P.memset("gpsimd", Sf[:], 0.0, [("Sf", p) for p in range(8)])

        def v3(ap):
            return ap.rearrange("p (u s) -> p u s", s=64)

        def u128(ap):
            return ap.rearrange("p (u x) -> p u x", x=128)

        def loadjob(ti, oc, a, z):
            isctx, idx = RW_ORDER1[ti]
            tg = 16 if isctx else idx
            sg_ = stg[a]
            P.dma("sync", sg_[:].rearrange("p d x n -> p (d x n)"), S["ops"][tg, oc], writes=[f"stg{a}"], sem=f"stg{a}")
            P.dma("sync", Vbq[z][:].rearrange("p u s -> p (u s)"), S["vb"][tg, oc], writes=[f"Vb{z}"], sem=f"Vb{z}")
            P.dma("sync", gamq[z][:].rearrange("p d u -> p (d u)"), S["gam"][tg, oc], writes=[f"gam{z}"], sem=f"gam{z}")
            yield
            for d in range(2):
                ar5 = ARq[a][d][:].rearrange("p u a (h s) -> p u a h s", h=2)
                kt4 = KTq[a][d][:].rearrange("p u (h s) -> p u h s", h=2)
                bt4 = BTq[a][d][:].rearrange("p u (h s) -> p u h s", h=2)
                for h2 in range(2):
                    sl = slice(h2 * 64, (h2 + 1) * 64)
                    P.cp("gpsimd", ar5[sl, :, 0, h2, :], v3(sg_[sl, d, 0, :]), [f"stg{a}"], [f"AR{a}{d}"])
                    P.cp("gpsimd", ar5[sl, :, 1, h2, :], v3(sg_[sl, d, 1, :]), [f"stg{a}"], [f"AR{a}{d}"])
                    P.cp("gpsimd", kt4[sl, :, h2, :], v3(sg_[sl, d, 2, :]), [f"stg{a}"], [f"KT{a}{d}"])
                    P.cp("gpsimd", bt4[sl, :, h2, :], v3(sg_[sl, d, 3, :]), [f"stg{a}"], [f"BT{a}{d}"])
                    yield

        def chain(ti, oc, q, d, z, a):
            AR, KT, BT, Vb = ARq[a][d], KTq[a][d], BTq[a][d], Vbq[z]
            ARn, KTn, BTn, Vbn = f"AR{a}{d}", f"KT{a}{d}", f"BT{a}{d}", f"Vb{z}"
            iv, fn = inv2[q][d], fin[q][d]
            IR = lambda nm: f"i{q}{d}_{nm}"
            FR = lambda nm: f"f{q}{d}_{nm}"
            mS, mC = (0, 2) if d == 0 else (2, 0)
            mSI = masks[:, mS:mS + 2, :].rearrange("p a b -> p (a b)").unsqueeze(1).broadcast_to([128, 4, 256])
            mCb = masks[:, mC, :].unsqueeze(1).broadcast_to([128, 4, 128])
            idb = identb[:].unsqueeze(1).broadcast_to([128, 4, 128])
            for src, srcn, dst, dstn in ((AR[:, :, 0, :], ARn, iv["Atok"], IR("Atok")), (BT[:], BTn, iv["Btok"], IR("Btok")),
                                         (KT[:], KTn, fn["Ktok"], FR("Ktok"))):
                j = nxt("pb")
                pbt = pb[j][:].bitcast(BF16)
                for u in range(4):
                    P.tr(pbt[:, u * 128:(u + 1) * 128], src[:, u, :], identb[:], [srcn, "identb"], [f"pb{j}"])
                P.cp("scalar", dst[:].rearrange("p u x -> p (u x)"), pbt[:, 0:512], [f"pb{j}"], [dstn])
            mSb = masks[:, mS, :].unsqueeze(1).broadcast_to([128, 4, 128])
            mIb = masks[:, mS + 1, :].unsqueeze(1).broadcast_to([128, 4, 128])

            def two_bank(mm_fn):
                j0, j1 = nxt("pb"), nxt("pb")
                for u in range(4):
                    mm_fn(u, pb[j0][:, u * 128:(u + 1) * 128], f"pb{j0}", pb[j1][:, u * 128:(u + 1) * 128], f"pb{j1}")
                return j0, j1

            for lhs, lhsn, dst, dstn in ((BT, BTn, iv["MQ"], IR("MQ")), (KT, KTn, fn["NP"], FR("NP"))):
                def mm_ab(u, o0, n0, o1, n1, lhs=lhs, lhsn=lhsn):
                    P.mm(o0, lhs[:, u, :], AR[:, u, 0, :], True, True, [lhsn, ARn], [n0])
                    P.mm(o1, lhs[:, u, :], AR[:, u, 1, :], True, True, [lhsn, ARn], [n1])
                j0, j1 = two_bank(mm_ab)
                P.tt("vector", dst[:, :, 0:128], u128(pb[j0][:]), mSb, ALU.mult, [f"pb{j0}", "masks"], [dstn])
                P.tt("vector", dst[:, :, 128:256], u128(pb[j1][:]), mIb, ALU.mult, [f"pb{j1}", "masks"], [dstn])
            j = nxt("pb")
            for u in range(4):
                P.mm(pb[j][:, u * 128:(u + 1) * 128], AR[:, u, 0, :], BT[:, u, :], True, True, [ARn, BTn], [f"pb{j}"])
            cur, curn, nx, nxn = iv["MWa"], IR("MWa"), iv["MWb"], IR("MWb")
            P.tt("vector", cur[:, :, 0, :], u128(pb[j][:]), mCb, ALU.mult, [f"pb{j}", "masks"], [curn])
            yield
            j = nxt("pb")
            for u in range(4):
                P.mm(pb[j][:, u * 128:(u + 1) * 128], iv["MQ"][:, u, 0:128], cur[:, u, 0, :], True, True, [IR("MQ"), curn], [f"pb{j}"])
            P.cp("scalar", nx[:, :, 0, :], u128(pb[j][:]), [f"pb{j}"], [nxn])
            P.tt("gpsimd", nx[:, :, 1, :], cur[:, :, 0, :], idb, ALU.add, [curn, "identb"], [nxn])
            j = nxt("pb")
            for u in range(4):
                P.mm(pb[j][:, u * 128:(u + 1) * 128], cur[:, u, 0, :], iv["MQ"][:, u, 0:128], True, True, [IR("MQ"), curn], [f"pb{j}"])
            curT, curTn, nxT, nxTn = iv["MTa"], IR("MTa"), iv["MTb"], IR("MTb")
            P.cp("scalar", curT[:], u128(pb[j][:]), [f"pb{j}"], [curTn])
            cur, curn, nx, nxn = nx, nxn, cur, curn
            yield
            for lev in range(1, 5):
                def mm_lev(u, o0, n0, o1, n1, cur=cur, curn=curn, curT=curT, curTn=curTn):
                    P.mm(o0, curT[:, u, :], cur[:, u, 0, :], True, True, [curTn, curn], [n0])
                    P.mm(o1, curT[:, u, :], cur[:, u, 1, :], True, True, [curTn, curn], [n1])
                j0, j1 = two_bank(mm_lev)
                P.cp("scalar", nx[:, :, 0, :], u128(pb[j0][:]), [f"pb{j0}"], [nxn])
                P.tt("vector", nx[:, :, 1, :], u128(pb[j1][:]), cur[:, :, 1, :], ALU.add, [f"pb{j1}", curn], [nxn])
                j = nxt("pb")
                for u in range(4):
                    P.mm(pb[j][:, u * 128:(u + 1) * 128], cur[:, u, 0, :], curT[:, u, :], True, True, [curn, curTn], [f"pb{j}"])
                P.cp("scalar", nxT[:], u128(pb[j][:]), [f"pb{j}"], [nxTn])
                cur, curn, nx, nxn = nx, nxn, cur, curn
                curT, curTn, nxT, nxTn = nxT, nxTn, curT, curTn
                yield
            j = nxt("pb")
            for u in range(4):
                P.mm(pb[j][:, u * 128:(u + 1) * 128], curT[:, u, :], cur[:, u, 1, :], True, True, [curTn, curn], [f"pb{j}"])
            P.tt("vector", nx[:, :, 1, :], u128(pb[j][:]), cur[:, :, 1, :], ALU.add, [f"pb{j}", curn], [nxn])
            W6, W6n = nx, nxn
            j = nxt("pb")
            for u in range(4):
                P.mm(pb[j][:, u * 64:(u + 1) * 64], fn["NP"][:, u, 0:128], Vb[:, u, :], True, True, [FR("NP"), Vbn], [f"pb{j}"])
            P.cp("scalar", fn["NVb"][:].rearrange("p u x -> p (u x)"), pb[j][:, 0:256], [f"pb{j}"], [FR("NVb")])
            yield

            def mm_d(u, o0, n0, o1, n1):
                P.mm(o0, W6[:, u, 1, :], iv["MQ"][:, u, 128:256], True, True, [W6n, IR("MQ")], [n0])
                P.mm(o1, W6[:, u, 1, :], iv["Btok"][:, u, :], True, True, [W6n, IR("Btok")], [n1])
            j0, j1 = two_bank(mm_d)
            P.cp("scalar", fn["XW"][:, :, 0:128], u128(pb[j0][:]), [f"pb{j0}"], [FR("XW")])
            P.cp("vector", fn["XW"][:, :, 128:256], u128(pb[j1][:]), [f"pb{j1}"], [FR("XW")])
            yield

            def mm_f(u, o0, n0, o1, n1):
                P.mm(o0, iv["Atok"][:, u, :], fn["XW"][:, u, 0:128], True, True, [IR("Atok"), FR("XW")], [n0])
                P.mm(o1, iv["Atok"][:, u, :], fn["XW"][:, u, 128:256], True, True, [IR("Atok"), FR("XW")], [n1])
            j0, j1 = two_bank(mm_f)
            P.tt("vector", fn["GY"][:], u128(pb[j0][:]), AR[:, :, 1, :], ALU.add, [f"pb{j0}", ARn], [FR("GY")])
            P.tt("vector", fn["GS"][:], u128(pb[j1][:]), idb, ALU.add, [f"pb{j1}", "identb"], [FR("GS")])
            yield

        def finish(ti, oc, q, z):
            isctx, idx = RW_ORDER1[ti]
            tg = 16 if isctx else idx
            sf, sb_ = fin[q]
            F0 = lambda nm: f"f{q}0_{nm}"
            F1 = lambda nm: f"f{q}1_{nm}"
            Vb, Vbn, gamz = Vbq[z], f"Vb{z}", gamq[z]
            SFR = ("Sf", oc)
            for u in range(4):
                yo = pf[:, u * 64:(u + 1) * 64]
                P.mm(yo, sf["NP"][:, u, 128:256], Vb[:, u, :], True, False, [F0("NP"), Vbn], ["pf"])
                P.mm(yo, sf["XW"][:, u, 0:128], sf["NVb"][:, u, :], False, False, [F0("XW"), F0("NVb")], ["pf"])
                P.mm(yo, sb_["NP"][:, u, 128:256], Vb[:, u, :], False, False, [F1("NP"), Vbn], ["pf"])
                P.mm(yo, sb_["XW"][:, u, 0:128], sb_["NVb"][:, u, :], False, False, [F1("XW"), F1("NVb")], ["pf"])
                P.mm(yo, sf["GY"][:, u, :], Sf[:, oc, :], False, True, [F0("GY"), SFR], ["pf"])
                so = pf[:, 256:320]
                P.mm(so, sf["Ktok"][:, u, :], Vb[:, u, :], True, False, [F0("Ktok"), Vbn], ["pf"])
                P.mm(so, sf["XW"][:, u, 128:256], sf["NVb"][:, u, :], False, False, [F0("XW"), F0("NVb")], ["pf"])
                P.mm(so, sf["GS"][:, u, :], Sf[:, oc, :], False, True, [F0("GS"), SFR], ["pf"])
                P.ts("vector", Sf[:, oc, :], so, gamz[:, 0, u:u + 1], None, ALU.mult, None, ["pf", f"gam{z}"], [SFR])
                yield
            P.cp("vector", YPs[:].rearrange("p u x -> p (u x)"), pf[:, 0:256], ["pf"], ["YPs"])
            P.dma("sync", S["yp"][tg, oc], YPs[:].rearrange("p u x -> p (u x)"), reads=["YPs"], writes=[("yp", tg, oc)], sem="YPs")
            j = nxt("pb")
            for u in range(4):
                so = pb[j][:, u * 64:(u + 1) * 64]
                P.mm(so, sb_["Ktok"][:, u, :], Vb[:, u, :], True, False, [F1("Ktok"), Vbn], [f"pb{j}"])
                P.mm(so, sb_["XW"][:, u, 128:256], sb_["NVb"][:, u, :], False, True, [F1("XW"), F1("NVb")], [f"pb{j}"])
            P.cp("scalar", SAs[:].rearrange("p u x -> p (u x)"), pb[j][:, 0:256], [f"pb{j}"], ["SAs"])
            P.dma("sync", S["sadd"][tg, oc], SAs[:].rearrange("p u x -> p (u x)"), reads=["SAs"], writes=[("sadd", tg, oc)], sem="SAs")
            P.dma("sync", S["gyb"][tg, oc], sb_["GY"][:].rearrange("p u x -> p (u x)"), reads=[F1("GY")], writes=[("gyb", tg, oc)], sem=F1("GY"))
            P.dma("sync", S["gsb"][tg, oc], sb_["GS"][:].rearrange("p u x -> p (u x)"), reads=[F1("GS")], writes=[("gsb", tg, oc)], sem=F1("GS"))
            yield


        NT = len(RW_ORDER1)
        NJ = NT * 8
        donef = set()

        def stream_L():
            for k in range(NJ):
                ti, oc = divmod(k, 8)
                yield ("load", k, lambda k=k: ((k < 3 or (("c0", k - 3) in donef and ("c1", k - 3) in donef)) and (k < 4 or ("fin", k - 4) in donef)),
                       lambda ti=ti, oc=oc, k=k: loadjob(ti, oc, k % 3, k % 4))

        def stream_C(d, par):
            for k in range(par, NJ, 2):
                ti, oc = divmod(k, 8)
                yield (f"c{d}", k, lambda k=k: (("load", k) in donef and (k < 2 or ("fin", k - 2) in donef)),
                       lambda ti=ti, oc=oc, k=k: chain(ti, oc, k % 2, d, k % 4, k % 3))

        def stream_F():
            for k in range(NJ):
                ti, oc = divmod(k, 8)
                yield ("fin", k, lambda k=k: (("c0", k) in donef and ("c1", k) in donef),
                       lambda ti=ti, oc=oc, k=k: finish(ti, oc, k % 2, k % 4))

        streams = [stream_L(), stream_C(0, 0), stream_C(1, 0), stream_C(0, 1), stream_C(1, 1), stream_F()]
        NS_ = len(streams)
        cur = [None] * NS_
        pend = [None] * NS_
        alive = [True] * NS_
        while any(alive):
            progressed = False
            for si in range(NS_):
                if not alive[si]:
                    continue
                if cur[si] is None:
                    if pend[si] is None:
                        try:
                            pend[si] = next(streams[si])
                        except StopIteration:
                            alive[si] = False
                            continue
                    kind, k, ready, mk = pend[si]
                    if not ready():
                        continue
                    cur[si] = (kind, k, mk())
                    pend[si] = None
                kind, k, gen = cur[si]
                try:
                    next(gen)
                    progressed = True
                except StopIteration:
                    donef.add((kind, k))
                    cur[si] = None
                    progressed = True
            assert progressed or not any(alive), "scheduler stuck"


def stage_rwkv2(P, io, G, S, src, xa):
    vec, identb = G["vec"], G["identb"]
    GN_EPS = 64e-5
    with P.phase("rwkv2"):
        wo = P.sb([64, 16, 1024], BF16)
        P.dma("gpsimd", wo[:], io["rwkv_wo"].rearrange("(h v) f -> v h f", v=64), writes=["wo"], sem="wo")
        lnw = P.sb([128, 8, 64], F32)
        lnb = P.sb([128, 8, 64], F32)
        P.dma("sync", lnw[:], io["lnw_st"], writes=["lnw"], sem="lnw")
        P.dma("sync", lnb[:], io["lnb_st"], writes=["lnb"], sem="lnb")
        big = {}
        for nm in ("yp", "sadd", "vst", "gst"):
            big[nm] = [P.sb([128, 8, 256], F32, f"l_{nm}{b}") for b in range(2)]
        for nm in ("gyb", "gsb"):
            big[nm] = [P.sb([128, 8, 512], BF16, f"l_{nm}{b}") for b in range(2)]
        gamb = [P.sb([128, 8, 4], F32) for _ in range(2)]
        bon = [P.sb([128, 8, 4], F32) for _ in range(2)]
        xt = [P.sb([128, 8, 256], F32) for _ in range(2)]
        Sb = P.sb([128, 8, 64], BF16)
        ysb2 = [P.sb([128, 8, 64], F32) for _ in range(2)]
        ysq2 = [P.sb([128, 8, 64], F32) for _ in range(2)]
        tmpS = P.sb([128, 8, 64], F32)
        yn2 = [P.sb([128, 8, 64], F32) for _ in range(2)]
        bv2 = [P.sb([128, 8, 64], F32) for _ in range(2)]
        ob2 = [P.sb([128, 8, 64], BF16) for _ in range(2)]
        st2 = [{nm: P.sb([128, 8], F32, f"g{k_}_" + nm) for nm in ("s1", "s2", "mean", "msq", "var", "lnv", "rstd")} for k_ in range(2)]
        OT2 = [P.sb([64, 16, 256], BF16, f"OT{q_}") for q_ in range(2)]
        py = [P.ps([128, 512], F32) for _ in range(2)]
        pS = P.ps([128, 512], F32)
        ptr = P.ps([128, 1024], F32)
        pw = [P.ps([128, 512], F32) for _ in range(2)]
        P.memset("gpsimd", Sb[:], 0.0, ["Sb"])

        def load(k):
            isctx, idx = RW_ORDER2[k]
            tg = 16 if isctx else idx
            b = k % 2
            for nm in ("yp", "sadd", "vst", "gst", "gyb", "gsb"):
                P.dma("sync", big[nm][b][:], S[nm][tg].rearrange("o p x -> p o x"), writes=[f"{nm}{b}"], sem=f"{nm}{b}")
            P.dma("sync", gamb[b][:].rearrange("p a b -> p (a b)"), S["gamb"][tg], writes=[f"gamb{b}"], sem=f"gamb{b}")
            P.dma("sync", bon[b][:].rearrange("p a b -> p (a b)"), S["bon"][tg], writes=[f"bon{b}"], sem=f"bon{b}")
            c0 = T if isctx else idx * 256
            P.dma("sync", xt[b][:], fm(src[:, c0:c0 + 256]), writes=[f"xt{b}"], sem=f"xt{b}")

        def emit_wo(kk):
            isctx_, idx_ = RW_ORDER2[kk]
            b_ = kk % 2
            c0_ = T if isctx_ else idx_ * 256
            _, _, gates_ = mod_scalars(G, 0, 0, isctx_)
            OT_ = OT2[kk % 2]
            for oc in range(8):
                j = oc % 2
                for h in range(16):
                    P.mm(pw[j][:, 0:256], wo[:, h, oc * 128:(oc + 1) * 128], OT_[:, h, :], h == 0, h == 15, ["wo", f"OT{kk % 2}"], [f"pw{j}"])
                P.stt(xt[b_][:, oc, :], pw[j][:, 0:256], gates_[oc], xt[b_][:, oc, :], ALU.mult, ALU.add, [f"pw{j}", f"xt{b_}", "modv"], [f"xt{b_}"])
            P.dma("sync", fm(xa[:, c0_:c0_ + 256]), xt[b_][:], reads=[f"xt{b_}"], writes=[("xa", kk)], sem=f"xt{b_}")

        load(0)
        for k, (isctx, idx) in enumerate(RW_ORDER2):
            b = k % 2
            OT, OTn = OT2[k % 2], f"OT{k % 2}"
            c0 = T if isctx else idx * 256
            _, _, gates = mod_scalars(G, 0, 0, isctx)
            bc = lambda ap: ap.unsqueeze(2).broadcast_to([128, 8, 64])
            def chain_part(u):
                us = slice(u * 64, (u + 1) * 64)
                q_ = u % 2
                for oc in range(8):
                    P.mm(py[q_][:, oc * 64:(oc + 1) * 64], big["gyb"][b][:, oc, u * 128:(u + 1) * 128], Sb[:, oc, :], True, True, [f"gyb{b}", "Sb"], [f"py{q_}"])
                for oc in range(8):
                    P.mm(pS[:, oc * 64:(oc + 1) * 64], big["gsb"][b][:, oc, u * 128:(u + 1) * 128], Sb[:, oc, :], True, True, [f"gsb{b}", "Sb"], ["pS"])
                pS3 = pS[:].rearrange("p (o v) -> p o v", v=64)
                P.tt("vector", tmpS[:], pS3, big["sadd"][b][:, :, us], ALU.add, ["pS", f"sadd{b}"], ["tmpS"])
                P.tt("vector", Sb[:], tmpS[:], bc(gamb[b][:, :, u]), ALU.mult, ["tmpS", f"gamb{b}"], ["Sb"])

            def read_part(u):
                us = slice(u * 64, (u + 1) * 64)
                q_ = u % 2
                ysb, ysq, yn, bv, ob, st = ysb2[q_], ysq2[q_], yn2[q_], bv2[q_], ob2[q_], st2[q_]
                N = lambda nm: f"{nm}{q_}"
                py3 = py[q_][:].rearrange("p (o v) -> p o v", v=64)
                P.tt("vector", ysb[:], py3, big["yp"][b][:, :, us], ALU.add, [f"py{q_}", f"yp{b}"], [N("ysb")])
                P.tt("gpsimd", bv[:], big["vst"][b][:, :, us], bc(bon[b][:, :, u]), ALU.mult, [f"vst{b}", f"bon{b}"], [N("bv")])
                yield
                P.op("vector", lambda e: e.tensor_reduce(out=st["s1"][:], in_=ysb[:], axis=AX.X, op=ALU.add), [N("ysb")], [N("s1")])
                P.tt("gpsimd", ysq[:], ysb[:], ysb[:], ALU.mult, [N("ysb")], [N("ysq")])
                yield
                P.op("vector", lambda e: e.tensor_reduce(out=st["s2"][:], in_=ysq[:], axis=AX.X, op=ALU.add), [N("ysq")], [N("s2")])
                P.ts("vector", st["mean"][:], st["s1"][:], 1.0 / 64, None, ALU.mult, None, [N("s1")], [N("mean")])
                P.tt("vector", st["msq"][:], st["mean"][:], st["mean"][:], ALU.mult, [N("mean")], [N("msq")])
                P.stt(st["var"][:], st["s2"][:], 1.0 / 64, st["msq"][:], ALU.mult, ALU.subtract, [N("s2"), N("msq")], [N("var")])
                yield
                P.act(st["lnv"][:], st["var"][:], AF.Ln, [N("var")], [N("lnv")], bias=GN_EPS)
                P.act(st["rstd"][:], st["lnv"][:], AF.Exp, [N("lnv")], [N("rstd")], scale=-0.5)
                P.tt("gpsimd", yn[:], ysb[:], bc(st["mean"][:]), ALU.subtract, [N("ysb"), N("mean")], [N("yn")])
                yield
                P.tt("vector", yn[:], yn[:], bc(st["rstd"][:]), ALU.mult, [N("yn"), N("rstd")], [N("yn")])
                yield
                P.tt("gpsimd", yn[:], yn[:], lnw[:], ALU.mult, [N("yn"), "lnw"], [N("yn")])
                yield
                P.tt("vector", yn[:], yn[:], lnb[:], ALU.add, [N("yn"), "lnb"], [N("yn")])
                yield
                P.tt("gpsimd", yn[:], yn[:], bv[:], ALU.add, [N("yn"), N("bv")], [N("yn")])
                yield
                P.tt("vector", ob[:], yn[:], big["gst"][b][:, :, us], ALU.mult, [N("yn"), f"gst{b}"], [N("ob")])
                yield
                ptb = ptr[:].bitcast(BF16)
                for oc in range(8):
                    P.tr(ptb[0:64, oc * 128:(oc + 1) * 128], ob[:, oc, :], identb[:], [N("ob"), "identb"], ["ptr"])
                P.cp("scalar", OT[:, :, us], ptb[0:64, 0:1024].rearrange("p (h t) -> p h t", t=64), ["ptr"], [OTn])
                yield

            def chain_all():
                for u in range(3, -1, -1):
                    chain_part(u)
                    yield

            jobs = [read_part(u) for u in range(3, -1, -1)]
            cgen = chain_all()
            next(cgen)
            if k >= 1:
                emit_wo(k - 1)
            if k + 1 < len(RW_ORDER2):
                load(k + 1)
            active = []
            started = 0
            while jobs or active:
                while jobs and len(active) < 2:
                    if started >= 1:
                        try:
                            next(cgen)
                        except StopIteration:
                            pass
                    active.append(jobs.pop(0))
                    started += 1
                for gen in list(active):
                    try:
                        next(gen)
                    except StopIteration:
                        active.remove(gen)
        emit_wo(len(RW_ORDER2) - 1)


def stage_qkv(P, io, G, hb, qtd, Kz, VA):
    vec, bones, perm = G["vec"], G["bones"], G["perm"]
    with P.phase("qkv"):
        wq = P.sb([128, 8, 1024], BF16)
        wkd = P.sb([128, 8, 512], BF16)
        wv = P.sb([128, 8, 256], BF16)
        P.dma("gpsimd", wq[:], fm(io["attn_wq"]), writes=["wq"], sem="wq")
        P.dma("gpsimd", wkd[:], fm(io["attn_wkd"]), writes=["wkd"], sem="wkd")
        P.dma("gpsimd", wv[:], fm(io["attn_wv"]), writes=["wv"], sem="wv")
        ht = [P.sb([128, 8, 512], BF16) for _ in range(2)]
        cs = [P.sb([128, 512], F32) for _ in range(2)]
        sn = [P.sb([128, 512], F32) for _ in range(2)]
        NB = 2
        qf = [P.sb([128, 512], F32) for _ in range(NB)]
        sqb = [P.sb([128, 512], BF16) for _ in range(NB)]
        lnv = [P.sb([128, 512], F32) for _ in range(NB)]
        rstd = [P.sb([128, 512], F32) for _ in range(NB)]
        qh = [P.sb([128, 512], F32) for _ in range(NB)]
        qhb = [P.sb([128, 512], BF16) for _ in range(NB)]
        t1 = [P.sb([128, 512], F32) for _ in range(NB)]
        t2 = [P.sb([128, 512], F32) for _ in range(NB)]
        qst = [P.sb([128, 8, 512], BF16) for _ in range(2)]
        pp = [P.ps([128, 512], F32) for _ in range(6)]
        cnt = [0, 0]

        def nxt():
            cnt[0] += 1
            return cnt[0] % 6

        P.memset("gpsimd", VA[:], 0.0, ["VA0"])
        P.memset("gpsimd", VA[:].rearrange("p k (j x) -> p k j x", x=65)[:, :, 0:5, 64:65], 1.0, ["VA0"])
        P.memset("gpsimd", Kz[0][64:128, :, :], 0.0, ["Kz0z"])
        P.memset("gpsimd", Kz[1][0:64, :, :], 0.0, ["Kz1z"])
        tiles = ALL_TILES

        def load(i):
            c0, tw, isctx = tiles[i]
            b = i % 2
            P.dma("sync", ht[b][:, :, :tw], fm(hb[:, c0:c0 + tw]), writes=[f"ht{b}"], sem=f"ht{b}")
            if not isctx:
                P.dma("sync", cs[b][:, :tw], io["cosT"][:, c0:c0 + tw], writes=[f"cs{b}"], sem=f"cs{b}")
                P.dma("sync", sn[b][:, :tw], io["sinT"][:, c0:c0 + tw], writes=[f"sn{b}"], sem=f"sn{b}")

        def normrope(wcols, nscal, dsts, b, tw, isctx, wname, dres="dstqk"):
            cnt[1] += 1
            n = cnt[1] % NB
            i = nxt()
            for c in range(8):
                P.mm(pp[i][:, :tw], wcols(c), ht[b][:, c, :tw], c == 0, c == 7, [wname, f"ht{b}"], [f"pp{i}"])
            P.cp("scalar", qf[n][:, :tw], pp[i][:, :tw], [f"pp{i}"], [f"qf{n}"])
            P.act(sqb[n][:, :tw], qf[n][:, :tw], AF.Square, [f"qf{n}"], [f"sqb{n}"])
            yield
            i = nxt()
            P.mm(pp[i][:, :tw], bones[:], sqb[n][:, :tw], True, True, ["bones", f"sqb{n}"], [f"pp{i}"])
            P.act(lnv[n][:, :tw], pp[i][:, :tw], AF.Ln, [f"pp{i}"], [f"lnv{n}"], bias=1e-6, scale=1.0 / 64)
            P.act(rstd[n][:, :tw], lnv[n][:, :tw], AF.Exp, [f"lnv{n}"], [f"rstd{n}"], scale=-0.5)
            yield
            P.stt(qh[n][:, :tw], qf[n][:, :tw], nscal, rstd[n][:, :tw], ALU.mult, ALU.mult, [f"qf{n}", f"rstd{n}", "vec"], [f"qh{n}"])
            if isctx:
                for dst, sl in dsts:
                    P.cp("vector", dst, qh[n][sl, :tw], [f"qh{n}"], [dres])
                return
            P.cp("vector", qhb[n][:, :tw], qh[n][:, :tw], [f"qh{n}"], [f"qhb{n}"])
            yield
            i = nxt()
            P.mm(pp[i][:, :tw], perm[:], qhb[n][:, :tw], True, True, ["perm", f"qhb{n}"], [f"pp{i}"])
            P.tt("vector", t1[n][:, :tw], qh[n][:, :tw], cs[b][:, :tw], ALU.mult, [f"qh{n}", f"cs{b}"], [f"t1{n}"])
            P.tt("vector", t2[n][:, :tw], pp[i][:, :tw], sn[b][:, :tw], ALU.mult, [f"pp{i}", f"sn{b}"], [f"t2{n}"])
            yield
            for dst, sl in dsts:
                P.tt("vector", dst, t1[n][sl, :tw], t2[n][sl, :tw], ALU.add, [f"t1{n}", f"t2{n}"], [dres])

        ALLP = slice(0, 128)
        load(0)
        for i, (c0, tw, isctx) in enumerate(tiles):
            b = i % 2
            if i + 1 < len(tiles):
                load(i + 1)
            jobs = []
            if not isctx:
                for oc in range(8):
                    jobs.append(normrope(lambda c, oc=oc: wq[:, c, oc * 128:(oc + 1) * 128], vec[:, 19, oc:oc + 1], [(qst[b][:, oc, :tw], ALLP)], b, tw, False, "wq",
                                         dres=(f"qst{b}", oc)))
            for g in range(4):
                jobs.append(normrope(lambda c, g=g: wkd[:, c, g * 128:(g + 1) * 128], vec[:, 20, 0:1],
                                     [(Kz[0][0:64, g, c0:c0 + tw], slice(0, 64)), (Kz[1][64:128, g, c0:c0 + tw], slice(64, 128))], b, tw, isctx, "wkd"))

            def vjob():
                for sub in range(tw // 128):
                    kt = c0 // 128 + sub
                    j = nxt()
                    for c in range(8):
                        P.mm(pp[j][:, 0:256], ht[b][:, c, sub * 128:(sub + 1) * 128], wv[:, c, :], c == 0, c == 7, ["wv", f"ht{b}"], [f"pp{j}"])
                    P.cp("scalar", VA[:, kt, 65:325].rearrange("p (g x) -> p g x", x=65)[:, :, 0:64],
                         pp[j][:, 0:256].rearrange("p (g d) -> p g d", d=64), [f"pp{j}", "VA0"], [("VA", kt)])
                    yield

            jobs.append(vjob())
            active = []
            while jobs or active:
                while jobs and len(active) < 2:
                    active.append(jobs.pop(0))
                for gen in list(active):
                    try:
                        next(gen)
                    except StopIteration:
                        active.remove(gen)
            if not isctx:
                P.dma("sync", fm(qtd[:, c0:c0 + tw]), qst[b][:, :, :tw], reads=[(f"qst{b}", oc) for oc in range(8)], writes=[("qtd", i)], sem=f"qst{b}")


def stage_attn(P, io, G, qtd, Kz, VA, xa):
    with P.phase("attn"):
        wo = P.sb([128, 8, 1024], BF16)
        P.dma("gpsimd", wo[:], fm(io["attn_wo"]), writes=["wo"], sem="wo")
        sel = P.sb([128, 2, 128], F32)
        P.dma("sync", sel[:], io["c_sel"], writes=["sel"], sem="sel")
        PT = [P.sb([128, 1024], BF16) for _ in range(3)]
        osb = [P.sb([128, 512], F32) for _ in range(2)]
        rb = [P.sb([128, 512], F32) for _ in range(2)]
        xt = P.sb([128, 8, 512], F32)
        QB = [P.sb([128, 8, 512], BF16) for _ in range(2)]
        psS = [P.ps([128, 1024], F32) for _ in range(2)]
        psO = [P.ps([128, 512], F32) for _ in range(2)]
        psB = P.ps([128, 512], F32)
        pX = [P.ps([128, 512], F32) for _ in range(1)]
        _, _, gates = mod_scalars(G, 1, 0, False)
        for k in range(2):
            P.memset("gpsimd", osb[k][:], 0.0, [f"osb{k}"])
        def loadq(qb):
            P.dma("sync", QB[qb % 2][:], fm(qtd[:, qb * 512:(qb + 1) * 512]), writes=[("QT", h, qb) for h in range(16)], sem=f"QB{qb % 2}")

        loadq(0)
        for qb in range(8):
            qsl = slice(qb * 512, (qb + 1) * 512)
            QT = QB[qb % 2]
            if qb + 1 < 8:
                loadq(qb + 1)
            P.dma("sync", xt[:], fm(xa[:, qsl]), writes=["xt"], sem="xt")
            steps = [(h, kp) for h in range(16) for kp in range(17)]

            def S(i):
                h, kp = steps[i]
                g, oc, h2 = h // 4, h // 2, h % 2
                for e_ in range(2):
                    kt = 2 * kp + e_
                    P.mm(psS[i % 2][:, e_ * 512:(e_ + 1) * 512], Kz[h2][:, g, kt * 128:(kt + 1) * 128], QT[:, oc, :], True, True,
                         ["Kz", ("QT", h, qb)], [f"psS{i % 2}"])

            def epi_a(h):
                o = h % 2
                P.cp("vector", osb[o][:], psO[o][:], [f"psO{o}"], [f"osb{o}"])

            def epi_b(h):
                oc, h2, o = h // 2, h % 2, h % 2
                hs = slice(h2 * 64, h2 * 64 + 64)
                P.mm(psB[:, :], sel[:, h2, :], osb[o][:], True, True, ["sel", f"osb{o}"], ["psB"])
                P.op("vector", lambda e, o=o, hs=hs: e.reciprocal(out=rb[o][hs, :], in_=psB[hs, :]), ["psB"], [f"rb{o}"])
                P.tt("gpsimd", QT[hs, oc, :], osb[o][hs, :], rb[o][hs, :], ALU.mult, [f"osb{o}", f"rb{o}"], [("QT", h, qb)])

            S(0)
            pend = {}
            for i, (h, kp) in enumerate(steps):
                g, h2, o = h // 4, h % 2, h % 2
                if i + 1 < len(steps):
                    S(i + 1)
                p_ = i % 3
                P.act(PT[p_][:], psS[i % 2][:, :], AF.Exp, [f"psS{i % 2}"], [f"PT{p_}"], scale=0.125)
                v0 = 65 + 65 * g if h2 == 0 else 1 + 65 * g
                for e_ in range(2):
                    kt = 2 * kp + e_
                    P.mm(psO[o][:, :], VA[:, kt, v0:v0 + 128], PT[p_][:, e_ * 512:(e_ + 1) * 512], kt == 0, kt == 33, [f"PT{p_}", "VA"], [f"psO{o}"])
                if kp == 16:
                    epi_a(h)
                    pend[i + 3] = h
                if i in pend:
                    epi_b(pend.pop(i))
            for k in sorted(pend):
                epi_b(pend[k])
            for oc in range(8):
                j = 0
                for c in range(8):
                    P.mm(pX[j][:, :], wo[:, c, oc * 128:(oc + 1) * 128], QT[:, c, :], c == 0, c == 7,
                         ["wo", ("QT", 2 * c, qb), ("QT", 2 * c + 1, qb)], [f"pX{j}"])
                P.stt(xt[:, oc, :], pX[j][:, :], gates[oc], xt[:, oc, :], ALU.mult, ALU.add, [f"pX{j}", "xt", "modv"], ["xt"])
            P.dma("sync", fm(xa[:, qsl]), xt[:], reads=["xt"], writes=[("xa", qb)], sem="xt")


IN_SHAPES = {
    "xin": [D, TT], "cvec": [128, 8, 2], "w_mod": [2, D, 6 * D], "b_mod": [2, 6 * D], "vecs": [128, NV, 8],
    "mlp_w1": [2, D, 4 * D], "mlp_w2": [2, 4 * D, D],
    "rwkv_wr": [D, D], "rwkv_wk": [D, D], "rwkv_wv": [D, D], "rwkv_wo": [D, D],
    "rwkv_w1": [2, D, 64], "rwkv_w2": [2, 64, D], "rwkv_a1": [2, D, 64], "rwkv_a2": [2, 64, D],
    "rwkv_g1": [D, 128], "rwkv_g2": [128, D], "lnw_st": [128, 8, 64], "lnb_st": [128, 8, 64],
    "attn_wq": [D, D], "attn_wkd": [D, 512], "attn_wv": [D, 256], "attn_wo": [D, D],
    "cosT": [128, T], "sinT": [128, T],
    "c_ident": [128, 128], "c_ones": [128, 128], "c_bones": [128, 128], "c_masks": [128, 4, 128],
    "c_perm": [128, 128], "c_rmask": [128, 256], "c_sel": [128, 2, 128],
}


class IO(dict):
    def __init__(self, nc):
        super().__init__()
        self.nc = nc
        self.used = []

    def __missing__(self, k):
        ap = self.nc.dram_tensor(k, IN_SHAPES[k], F32, kind="ExternalInput").ap()
        self[k] = ap
        self.used.append(k)
        return ap

    def scratch(self, name, shape, dtype):
        return self.nc.dram_tensor(name, list(shape), dtype, kind="Internal").ap()

    def output(self, name, shape, dtype=F32):
        return self.nc.dram_tensor(name, list(shape), dtype, kind="ExternalOutput").ap()


def build(stages="all", dbg=None):
    nc = bass.Bass("TRN2", target_bir_lowering=False)
    io = IO(nc)
    P = Prog(nc)
    G = {}
    outs = {}
    stage_init(P, io, G)
    xa = io.scratch("xa", [D, TT], F32)
    hb = io.scratch("hb", [D, TT], BF16)
    if stages == "t_mlp":
        outs["dbg_h"] = io.output("dbg_h", [D, TT], BF16)
        stage_norm(P, io, G, "n_t", io["xin"], ALL_TILES,
                   lambda ic: mod_scalars(G, 0, 1, ic)[0], lambda ic: mod_scalars(G, 0, 1, ic)[1],
                   lambda c0, tw, ic: fm(hb[:, c0:c0 + tw]), BF16)
        with P.phase("copy"):
            P.dma("sync", xa, io["xin"], writes=["xa"], sem="cpa")
            P.dma("sync", outs["dbg_h"], hb, writes=["o"], sem="cpb")
        stage_mlp(P, io, G, 0, ALL_TILES, xa, hb)
        outs["y"] = io.output("y", [D, TT])
        fin = [G["vec"][:, 4, c:c + 1] for c in range(8)]
        stage_norm(P, io, G, "final", xa, ALL_TILES, lambda ic: fin, lambda ic: None,
                   lambda c0, tw, ic: fm(outs["y"][:, c0:c0 + tw]), F32)
    if stages in ("all", "l0", "l1pre"):
        hp = io.scratch("hp", [D, 4608], F32)
        S = rw_scratch(io)
        with P.phase("zpad"):
            z = P.sb([128, 8, 64], F32)
            P.memset("vector", z[:], 0.0, ["z"])
            for k, o in enumerate((0, 64 + T, 4224, 4288 + C)):
                P.dma("sync", fm(hp[:, o:o + 64]), z[:], reads=["z"], writes=[("hpz", k)], sem=f"z{k}")

        def hdst(c0, tw, ic):
            o = 4288 if ic else 64 + c0
            return fm(hp[:, o:o + tw])

        def hbdst(c0, tw, ic):
            return fm(hb[:, c0:c0 + tw])

        def ms(l, kind, which):
            return lambda ic: mod_scalars(G, l, kind, ic)[which]

        stage_norm(P, io, G, "n_mix0", io["xin"], ALL_TILES, ms(0, 0, 0), ms(0, 0, 1), hdst, F32)
        stage_rwkv1a(P, io, G, hp, S)
        stage_rwkv1b(P, io, G, S)
        stage_rwkv2(P, io, G, S, io["xin"], xa)
        stage_norm(P, io, G, "n_mlp0", xa, ALL_TILES, ms(0, 1, 0), ms(0, 1, 1), hbdst, BF16)
        stage_mlp(P, io, G, 0, ALL_TILES, xa, hb)
        if stages == "l0":
            outs["y"] = io.output("y", [D, TT])
            with P.phase("copyout"):
                P.dma("sync", outs["y"], xa, writes=["o"], sem="cpa")
        else:
            stage_norm(P, io, G, "n_mix1", xa, ALL_TILES, ms(1, 0, 0), ms(1, 0, 1), hbdst, BF16)
            with P.scope():
                QT = io.scratch("qtd", [D, T], BF16)
                Kz = [P.ssb([128, 4, TT], BF16, f"Kz{k}") for k in range(2)]
                VA = P.ssb([128, 34, 390], BF16, "VA")
                stage_qkv(P, io, G, hb, QT, Kz, VA)
                stage_attn(P, io, G, QT, Kz, VA, xa)
            if stages == "l1pre":
                outs["y"] = io.output("y", [D, TT])
                with P.phase("copyout"):
                    P.dma("sync", outs["y"], xa, writes=["o"], sem="cpa")
            else:
                stage_norm(P, io, G, "n_mlp1", xa, LAT_TILES, ms(1, 1, 0), ms(1, 1, 1), hbdst, BF16)
                stage_mlp(P, io, G, 1, LAT_TILES, xa, hb)
                outs["y"] = io.output("y", [D, T])
                fin = [G["vec"][:, 4, c:c + 1] for c in range(8)]
                stage_norm(P, io, G, "final", xa, LAT_TILES, lambda ic: fin, lambda ic: None,
                           lambda c0, tw, ic: fm(outs["y"][:, c0:c0 + tw]), F32)
    if stages == "t_rwkv":
        hp = io.scratch("hp", [D, 4608], F32)
        S = rw_scratch(io)
        with P.phase("zpad"):
            z = P.sb([128, 8, 64], F32)
            P.memset("vector", z[:], 0.0, ["z"])
            for k, o in enumerate((0, 64 + T, 4224, 4288 + C)):
                P.dma("sync", fm(hp[:, o:o + 64]), z[:], reads=["z"], writes=[("hpz", k)], sem=f"z{k}")
        def hdst(c0, tw, ic):
            o = 4288 if ic else 64 + c0
            return fm(hp[:, o:o + tw])
        stage_norm(P, io, G, "n_mix0", io["xin"], ALL_TILES,
                   lambda ic: mod_scalars(G, 0, 0, ic)[0], lambda ic: mod_scalars(G, 0, 0, ic)[1], hdst, F32)
        stage_rwkv1(P, io, G, hp, S)
        stage_rwkv2(P, io, G, S, io["xin"], xa)
        outs["y"] = io.output("y", [D, TT])
        with P.phase("copyout"):
            P.dma("sync", outs["y"], xa, writes=["o"], sem="cpa")
    P.close()
    return nc, io.used, list(outs.keys()), P


def fmv(v):
    return np.ascontiguousarray(np.asarray(v, np.float32).reshape(8, 128).T)


def host_consts():
    c = {}
    c["c_ident"] = np.eye(128, dtype=np.float32)
    c["c_ones"] = np.ones((128, 128), np.float32)
    blk = np.zeros((128, 128), np.float32)
    blk[:64, :64] = 1
    blk[64:, 64:] = 1
    c["c_bones"] = blk
    i = np.arange(64)
    us = (i[:, None] < i[None, :]).astype(np.float32)
    ui = (i[:, None] <= i[None, :]).astype(np.float32)
    m = np.zeros((128, 4, 128), np.float32)
    for k, mk in enumerate([us, ui, us.T, ui.T]):
        m[:64, k, :64] = mk
        m[64:, k, 64:] = mk
    c["c_masks"] = m
    Pm = np.zeros((128, 128), np.float32)
    for d in range(128):
        if d % 32 < 16:
            Pm[d, d + 16] = -1.0
        else:
            Pm[d, d - 16] = 1.0
    c["c_perm"] = np.ascontiguousarray(Pm.T)
    sel = np.zeros((128, 2, 128), np.float32)
    sel[64, 0, :] = 1.0
    sel[63, 1, :] = 1.0
    c["c_sel"] = sel
    rm = np.ones((128, 256), np.float32)
    rm[:, ::64] = 0
    c["c_rmask"] = rm
    t = np.arange(T)
    row = (t // 64).astype(np.float32)
    col = (t % 64).astype(np.float32)
    freqs = (np.float32(10000.0) ** (-np.arange(0, 32, 2, dtype=np.float32) / np.float32(32))).astype(np.float32)
    ang = np.zeros((64, T), np.float32)
    for d in range(64):
        pos = row if d < 32 else col
        ang[d] = pos * freqs[d % 16]
    c["cosT"] = np.ascontiguousarray(np.concatenate([np.cos(ang), np.cos(ang)], 0).astype(np.float32))
    c["sinT"] = np.ascontiguousarray(np.concatenate([np.sin(ang), np.sin(ang)], 0).astype(np.float32))
    return c


def host_inputs(inp, b):
    f = lambda k: np.asarray(inp[k], np.float32)
    d = {}
    d["xin"] = np.ascontiguousarray(np.concatenate([f("x")[b].T, f("ctx")[b].T], axis=1))
    d["cvec"] = np.ascontiguousarray(np.stack([fmv(f("c")[b]), fmv(f("c_ctx"))], axis=-1))
    return d


def host_shared(inp):
    f = lambda k: np.asarray(inp[k], np.float32)
    s = dict(host_consts())
    s["w_mod"] = f("w_mod")
    s["b_mod"] = f("b_mod")
    vl = [f("norm_mix")[0], f("norm_mix")[1], f("norm_mlp")[0], f("norm_mlp")[1], f("final_norm")]
    vl += [f("rwkv_mu")[0, j] for j in range(6)]
    vl += [f("rwkv_w0")[0, 0], f("rwkv_w0")[0, 1], f("rwkv_a0")[0, 0], f("rwkv_a0")[0, 1]]
    vl += [f("rwkv_k_k")[0], f("rwkv_k_a")[0], np.zeros(D, np.float32), f("rwkv_r_k")[0].reshape(-1)]
    vl += [np.tile(f("attn_q_norm")[0], 16), np.tile(f("attn_k_norm")[0], 16)]
    assert len(vl) == NV
    s["vecs"] = np.ascontiguousarray(np.stack([fmv(v) for v in vl], axis=1))
    s["mlp_w1"] = f("mlp_w1")
    s["mlp_w2"] = f("mlp_w2")
    for k in ("wr", "wk", "wv", "wo", "w1", "w2", "a1", "a2", "g1", "g2"):
        s["rwkv_" + k] = f("rwkv_" + k)[0]
    lw = f("rwkv_ln_w")[0].reshape(8, 2, 64)
    lb = f("rwkv_ln_b")[0].reshape(8, 2, 64)
    s["lnw_st"] = np.ascontiguousarray(np.repeat(lw.transpose(1, 0, 2), 64, axis=0))
    s["lnb_st"] = np.ascontiguousarray(np.repeat(lb.transpose(1, 0, 2), 64, axis=0))
    wqkv = f("attn_wqkv")[0]
    s["attn_wq"] = np.ascontiguousarray(wqkv[:, :1024])
    wk = wqkv[:, 1024:1280].reshape(D, 4, 64)
    s["attn_wkd"] = np.ascontiguousarray(np.concatenate([wk, wk], axis=2).reshape(D, 512))
    s["attn_wv"] = np.ascontiguousarray(wqkv[:, 1280:1536])
    s["attn_wo"] = f("attn_wo")[0]
    return s


_CACHE = {}


def kernel(**inputs):
    if "prog" not in _CACHE:
        _CACHE["prog"] = build("all")
    nc, used, outnames, _ = _CACHE["prog"]
    shared = host_shared(inputs)
    in_maps = []
    for b in range(NCORES):
        hi = host_inputs(inputs, b)
        hi.update(shared)
        in_maps.append({k: hi[k] for k in used})
    res = run_bass_kernel_spmd(nc, in_maps, core_ids=list(range(NCORES)))
    out = np.stack([np.ascontiguousarray(res.results[b]["y"].T) for b in range(NCORES)], axis=0)
    return out.astype(np.float32)
```

```python
from contextlib import ExitStack, contextmanager
import re as re_mod
import numpy as np
import concourse.bass as bass
import concourse.mybir as mybir
from concourse.bass_utils import run_bass_kernel_spmd

F32 = mybir.dt.float32
BF16 = mybir.dt.bfloat16
AF = mybir.ActivationFunctionType
ALU = mybir.AluOpType
AX = mybir.AxisListType

D = 1024
T = 4096
C = 256
TT = T + C
NCORES = 8
C0 = float(np.exp(-0.5))
NV = 21
ENGS = ("tensor", "vector", "scalar", "gpsimd", "sync")


class Prog:
    def __init__(self, nc):
        self.nc = nc
        self.ges = ExitStack()
        self.sems = {}
        self.cnt = {}
        self.dpool = {False: [], True: []}
        self.seen = {e: {} for e in ENGS}
        self.n = 0
        self.pes = None
        self.total_ops = 0

    def _alloc(self, es, fn, shape, dtype, name):
        self.n += 1
        return es.enter_context(fn(name or f"t{self.n}", list(shape), dtype))

    def gsb(self, shape, dtype, name=None):
        return self._alloc(self.ges, self.nc.sbuf_tensor, shape, dtype, name)

    def sb(self, shape, dtype, name=None):
        return self._alloc(self.pes, self.nc.sbuf_tensor, shape, dtype, name)

    @contextmanager
    def scope(self):
        self.ses = ExitStack()
        yield self
        self.ses.close()
        self.ses = None

    def ssb(self, shape, dtype, name=None):
        return self._alloc(self.ses, self.nc.sbuf_tensor, shape, dtype, name)

    def ps(self, shape, dtype, name=None):
        return self._alloc(self.pes, self.nc.psum_tensor, shape, dtype, name)

    @contextmanager
    def phase(self, name):
        self.ops = []
        self.last_w = {}
        self.readers = {}
        self.last_dma = {}
        self.pes = ExitStack()
        self.pname = name
        yield self
        self._emit()
        self.pes.close()
        self.pes = None

    _PSUM_RE = re_mod.compile(r"^(pp|pa|pb|pq|pf|ps\w*|pX|py|pS|ptr|pw)\d*$")

    ns = None
    ns_set = frozenset()

    def _deps(self, reads, writes):
        if self.ns is not None:
            reads = tuple((r, self.ns) if r in self.ns_set else r for r in reads)
            writes = tuple((w, self.ns) if w in self.ns_set else w for w in writes)
        extra = tuple(r for r in reads if isinstance(r, str) and self._PSUM_RE.match(r) and r not in writes)
        if extra:
            writes = tuple(writes) + extra
        deps = {}
        for r in reads:
            if r in self.last_w:
                deps.setdefault(self.last_w[r], set()).add("RAW")
        for w in writes:
            if w in self.last_w:
                deps.setdefault(self.last_w[w], set()).add("WAW")
            for rd in self.readers.get(w, ()):
                deps.setdefault(rd, set()).add("WAR")
        idx = len(self.ops)
        for r in reads:
            self.readers.setdefault(r, []).append(idx)
        for w in writes:
            self.last_w[w] = idx
            self.readers[w] = []
        return deps

    def op(self, eng, fn, reads=(), writes=()):
        deps = self._deps(tuple(reads), tuple(writes))
        self.ops.append(dict(eng=eng, fn=fn, deps=deps, dma=None))
        return len(self.ops) - 1

    def dma(self, queue, out, in_, reads=(), writes=(), sem=None):
        deps = self._deps(tuple(reads), tuple(writes))
        prev = self.last_dma.get(sem)
        if prev is not None:
            deps.setdefault(prev, set()).add("SER")
        idx = len(self.ops)
        self.last_dma[sem] = idx
        self.ops.append(dict(eng=queue, fn=lambda e: e.dma_start(out=out, in_=in_), deps=deps, dma=sem))
        return idx

    def _emit(self):
        nc = self.nc
        ops = self.ops
        if self.last_dma:
            ops.append(dict(eng="sync", fn=None, deps={i: {"FIN"} for i in self.last_dma.values()}, dma=None))
        self.total_ops += len(ops)

        def needs_wait(x, d, kinds):
            if d["dma"] is not None or x["dma"] is not None:
                return True
            if d["eng"] != x["eng"]:
                return True
            if x["eng"] == "tensor":
                return False
            return bool(kinds & {"RAW", "FIN"})

        signal = [False] * len(ops)
        for x in ops:
            for di, kinds in x["deps"].items():
                d = ops[di]
                if d["dma"] is None and needs_wait(x, d, kinds):
                    signal[di] = True
        dkeys = {}
        nk = {False: 0, True: 0}
        for o in ops:
            if o["dma"] is not None and o["dma"] not in dkeys:
                sw = o["eng"] == "gpsimd"
                dkeys[o["dma"]] = (sw, nk[sw])
                nk[sw] += 1
        for sw in (False, True):
            while len(self.dpool[sw]) < nk[sw]:
                h = self.ges.enter_context(nc.semaphore(f"dq{int(sw)}_{len(self.dpool[sw])}"))
                self.dpool[sw].append([h, 0])
        for e in ENGS:
            if e not in self.sems:
                self.sems[e] = self.ges.enter_context(nc.semaphore(f"e_{e}"))
        token = [None] * len(ops)
        for i, o in enumerate(ops):
            if o["dma"] is not None:
                dk = dkeys[o["dma"]]
                slot = self.dpool[dk[0]][dk[1]]
                slot[1] += 16
                token[i] = (("d", dk), slot[1])
            elif signal[i]:
                self.cnt[o["eng"]] = self.cnt.get(o["eng"], 0) + 1
                token[i] = (("e", o["eng"]), self.cnt[o["eng"]])
        per_eng = {e: [] for e in ENGS}
        for i, o in enumerate(ops):
            per_eng[o["eng"]].append(i)

        def semh(key):
            return self.dpool[key[1][0]][key[1][1]][0] if key[0] == "d" else self.sems[key[1]]

        def run(engname, eng):
            seen = self.seen[engname]
            for i in per_eng[engname]:
                o = ops[i]
                waits = {}
                for di, kinds in o["deps"].items():
                    d = ops[di]
                    if not needs_wait(o, d, kinds):
                        continue
                    key, val = token[di]
                    if waits.get(key, 0) < val:
                        waits[key] = val
                for key, val in waits.items():
                    if seen.get(key, 0) >= val:
                        continue
                    seen[key] = val
                    eng.wait_ge(semh(key), val)
                if o["fn"] is None:
                    continue
                ins = o["fn"](eng)
                if o["dma"] is not None:
                    ins.then_inc(semh(token[i][0]), 16)
                elif signal[i]:
                    ins.then_inc(self.sems[engname], 1)

        with nc.Block() as block:
            @block.sync
            def _(e):
                run("sync", e)

            @block.tensor
            def _(e):
                run("tensor", e)

            @block.vector
            def _(e):
                run("vector", e)

            @block.scalar
            def _(e):
                run("scalar", e)

            @block.gpsimd
            def _(e):
                run("gpsimd", e)

    def close(self):
        self.ges.close()

    def mm(self, out, lhsT, rhs, start, stop, r, w):
        self.op("tensor", lambda e: e.matmul(out, lhsT=lhsT, rhs=rhs, start=start, stop=stop), r, w)

    def tr(self, out, in_, ident, r, w):
        self.op("tensor", lambda e: e.transpose(out, in_, ident), r, w)

    def tt(self, eng, out, in0, in1, op, r, w):
        self.op(eng, lambda e: e.tensor_tensor(out=out, in0=in0, in1=in1, op=op), r, w)

    def ts(self, eng, out, in0, s1, s2, op0, op1, r, w):
        if op1 is None:
            self.op(eng, lambda e: e.tensor_scalar(out=out, in0=in0, scalar1=s1, scalar2=None, op0=op0), r, w)
        else:
            self.op(eng, lambda e: e.tensor_scalar(out=out, in0=in0, scalar1=s1, scalar2=s2, op0=op0, op1=op1), r, w)

    def stt(self, out, in0, scalar, in1, op0, op1, r, w):
        self.op("vector", lambda e: e.scalar_tensor_tensor(out=out, in0=in0, scalar=scalar, in1=in1, op0=op0, op1=op1), r, w)

    def act(self, out, in_, func, r, w, bias=None, scale=None):
        kw = {}
        if bias is not None:
            kw["bias"] = bias
        if scale is not None:
            kw["scale"] = scale
        self.op("scalar", lambda e: e.activation(out=out, in_=in_, func=func, **kw), r, w)

    def cp(self, eng, out, in_, r, w):
        if eng == "scalar":
            self.op(eng, lambda e: e.activation(out=out, in_=in_, func=AF.Copy), r, w)
        else:
            self.op(eng, lambda e: e.tensor_copy(out=out, in_=in_), r, w)

    def memset(self, eng, ap, val, w):
        self.op(eng, lambda e: e.memset(ap, val), (), w)


def fm(ap2d):
    return ap2d.rearrange("(c p) n -> p c n", p=128)


LAT_TILES = [(i * 512, 512, False) for i in range(8)]
ALL_TILES = LAT_TILES + [(T, 256, True)]


def stage_init(P, io, G):
    nc = P.nc
    G["identf"] = P.gsb([128, 128], F32, "identf")
    G["identb"] = P.gsb([128, 128], BF16, "identb")
    G["onesb"] = P.gsb([128, 128], BF16, "onesb")
    G["bones"] = P.gsb([128, 128], BF16, "bones")
    G["masks"] = P.gsb([128, 4, 128], BF16, "masks")
    G["perm"] = P.gsb([128, 128], BF16, "perm")
    G["rmask"] = P.gsb([128, 256], F32, "rmask")
    G["vec"] = P.gsb([128, NV, 8], F32, "vec")
    G["modv"] = P.gsb([128, 2, 6, 8, 2], F32, "modv")
    G["gg"] = P.gsb([128, 2, 2, 8, 2], F32, "gg")
    with P.phase("init"):
        P.dma("sync", G["identf"][:], io["c_ident"], writes=["identf"], sem="identf")
        P.dma("sync", G["rmask"][:], io["c_rmask"], writes=["rmask"], sem="rmask")
        P.dma("sync", G["vec"][:], io["vecs"], writes=["vec"], sem="vec")
        P.dma("gpsimd", G["identb"][:], io["c_ident"], writes=["identb"], sem="identb")
        P.dma("gpsimd", G["onesb"][:], io["c_ones"], writes=["onesb"], sem="onesb")
        P.dma("gpsimd", G["bones"][:], io["c_bones"], writes=["bones"], sem="bones")
        P.dma("gpsimd", G["masks"][:], io["c_masks"], writes=["masks"], sem="masks")
        P.dma("gpsimd", G["perm"][:], io["c_perm"], writes=["perm"], sem="perm")
        vec = G["vec"]
        P.ts("vector", vec[:, 17, :], vec[:, 16, :], -1.0, 1.0, ALU.mult, ALU.add, ["vec"], ["vec"])
        sv = P.sb([128, 8, 2], F32)
        svs = P.sb([128, 8, 2], F32)
        P.dma("sync", sv[:], io["cvec"], writes=["sv"], sem="sv")
        P.act(svs[:], sv[:], AF.Silu, ["sv"], ["svs"])
        brow = P.sb([2, 2 * 6144], F32)
        row = P.sb([2, 2 * 6144], F32)
        P.dma("sync", brow[:], io["b_mod"].rearrange("l n -> (l n)").partition_broadcast(2), writes=["brow"], sem="brow")
        wt = [P.sb([128, 8, 512], F32) for _ in range(2)]
        psr = [P.ps([128, 512], F32) for _ in range(2)]
        pst = P.ps([128, 512], F32)
        k = 0
        for l in range(2):
            for nb in range(12):
                b = k % 2
                k += 1
                P.dma("sync", wt[b][:], fm(io["w_mod"][l, :, nb * 512:(nb + 1) * 512]), writes=[f"wt{b}"], sem=f"wt{b}")
                for c in range(8):
                    P.mm(psr[b][0:2, :], svs[:, c, :], wt[b][:, c, :], c == 0, c == 7, ["svs", f"wt{b}"], [f"psr{b}"])
                o = l * 6144 + nb * 512
                P.tt("vector", row[:, o:o + 512], psr[b][0:2, :], brow[:, o:o + 512], ALU.add, [f"psr{b}", "brow"], ["row"])
        for l in range(2):
            for blk in range(48):
                o = l * 6144 + blk * 128
                P.tr(pst[:, l * 96 + blk * 2:l * 96 + blk * 2 + 2], row[0:2, o:o + 128], G["identf"][0:2, 0:2], ["row", "identf"], ["pst"])
        P.cp("vector", G["modv"][:].rearrange("p l m c j -> p (l m c j)"), pst[:, 0:192], ["pst"], ["modv"])
        modv, gg = G["modv"], G["gg"]
        for l in range(2):
            for kind in range(2):
                sc = modv[:, l, 1 + 3 * kind, :, :]
                nv = vec[:, (0 if kind == 0 else 2) + l, :].unsqueeze(2).broadcast_to([128, 8, 2])
                P.ts("vector", gg[:, l, kind, :, :], sc, 1.0, None, ALU.add, None, ["modv"], ["gg"])
                P.tt("vector", gg[:, l, kind, :, :], gg[:, l, kind, :, :], nv, ALU.mult, ["gg", "vec"], ["gg"])


def mod_scalars(G, l, kind, isctx):
    j = 1 if isctx else 0
    gains = [G["gg"][:, l, kind, c, j:j + 1] for c in range(8)]
    shifts = [G["modv"][:, l, 3 * kind, c, j:j + 1] for c in range(8)]
    gates = [G["modv"][:, l, 3 * kind + 2, c, j:j + 1] for c in range(8)]
    return gains, shifts, gates


def stage_norm(P, io, G, name, src, tiles, gains_fn, shifts_fn, dst_fn, out_dtype):
    with P.phase(name):
        xt = [P.sb([128, 8, 512], F32) for _ in range(2)]
        sq = P.sb([128, 8, 512], BF16)
        lnv = P.sb([128, 512], F32)
        rstd = P.sb([128, 512], F32)
        tmp = [P.sb([128, 512], F32) for _ in range(2)]
        ho = [P.sb([128, 8, 512], out_dtype) for _ in range(2)]
        ps = [P.ps([128, 512], F32) for _ in range(2)]

        def load(i):
            c0, tw, _ = tiles[i]
            b = i % 2
            P.dma("sync", xt[b][:, :, :tw], fm(src[:, c0:c0 + tw]), writes=[f"xt{b}"], sem=f"xt{b}")

        load(0)
        for i, (c0, tw, isctx) in enumerate(tiles):
            b = i % 2
            if i + 1 < len(tiles):
                load(i + 1)
            gains = gains_fn(isctx)
            shifts = shifts_fn(isctx)
            P.act(sq[:, :, :tw], xt[b][:, :, :tw], AF.Square, [f"xt{b}"], ["sq"])
            for c in range(8):
                P.mm(ps[b][:, :tw], G["onesb"][:], sq[:, c, :tw], c == 0, c == 7, ["sq", "onesb"], [f"ps{b}"])
            P.act(lnv[:, :tw], ps[b][:, :tw], AF.Ln, [f"ps{b}"], ["lnv"], bias=1e-6, scale=1.0 / D)
            P.act(rstd[:, :tw], lnv[:, :tw], AF.Exp, ["lnv"], ["rstd"], scale=-0.5)
            for c in range(8):
                if shifts is None:
                    P.stt(ho[b][:, c, :tw], xt[b][:, c, :tw], gains[c], rstd[:, :tw], ALU.mult, ALU.mult,
                          [f"xt{b}", "rstd", "vec", "gg"], [f"ho{b}"])
                else:
                    t = tmp[c % 2]
                    P.stt(t[:, :tw], xt[b][:, c, :tw], gains[c], rstd[:, :tw], ALU.mult, ALU.mult,
                          [f"xt{b}", "rstd", "vec", "gg"], [f"tmp{c % 2}"])
                    P.act(ho[b][:, c, :tw], t[:, :tw], AF.Identity, [f"tmp{c % 2}", "modv"], [f"ho{b}"], bias=shifts[c])
            P.dma("gpsimd", dst_fn(c0, tw, isctx), ho[b][:, :, :tw], reads=[f"ho{b}"], writes=[("dst", i)], sem=f"ho{b}")


def stage_mlp(P, io, G, l, tiles, xa, hb):
    for half in range(2):
        with P.phase(f"mlp{l}{half}"):
            w1 = P.sb([128, 8, 2048], BF16)
            w2 = P.sb([128, 16, 1024], BF16)
            for q in range(2):
                P.dma("gpsimd", w1[:, :, q * 1024:(q + 1) * 1024],
                      fm(io["mlp_w1"][l, :, half * 2048 + q * 1024: half * 2048 + (q + 1) * 1024]), writes=["w1"], sem=f"w1{q}")
                P.dma("gpsimd", w2[:, q * 8:(q + 1) * 8, :],
                      io["mlp_w2"][l, half * 2048 + q * 1024: half * 2048 + (q + 1) * 1024, :].rearrange("(f p) n -> p f n", p=128),
                      writes=["w2"], sem=f"w2{q}")
            xt = [P.sb([128, 8, 512], F32) for _ in range(2)]
            ht = [P.sb([128, 8, 512], BF16) for _ in range(2)]
            h1 = P.sb([128, 16, 512], BF16)
            r1 = [P.sb([128, 512], F32) for _ in range(2)]
            ps = [P.ps([128, 512], F32) for _ in range(4)]

            def load(i):
                c0, tw, _ = tiles[i]
                b = i % 2
                P.dma("sync", ht[b][:, :, :tw], fm(hb[:, c0:c0 + tw]), writes=[f"ht{b}"], sem=f"ht{b}")
                P.dma("sync", xt[b][:, :, :tw], fm(xa[:, c0:c0 + tw]), reads=[("xa", i)], writes=[f"xt{b}"], sem=f"xt{b}")

            load(0)
            for i, (c0, tw, isctx) in enumerate(tiles):
                b = i % 2
                if i + 1 < len(tiles):
                    load(i + 1)
                _, _, gates = mod_scalars(G, l, 1, isctx)
                for fc in range(16):
                    pb = fc % 2
                    for c in range(8):
                        P.mm(ps[pb][:, :tw], w1[:, c, fc * 128:(fc + 1) * 128], ht[b][:, c, :tw], c == 0, c == 7,
                             ["w1", f"ht{b}"], [f"ps{pb}"])
                    P.act(r1[pb][:, :tw], ps[pb][:, :tw], AF.Relu, [f"ps{pb}"], [f"r1{pb}"])
                    P.tt("gpsimd", h1[:, fc, :tw], r1[pb][:, :tw], r1[pb][:, :tw], ALU.mult, [f"r1{pb}"], [("h1", fc)])
                for oc in range(8):
                    pb = 2 + oc % 2
                    for fc in range(16):
                        P.mm(ps[pb][:, :tw], w2[:, fc, oc * 128:(oc + 1) * 128], h1[:, fc, :tw], fc == 0, fc == 15,
                             ["w2", ("h1", fc)], [f"ps{pb}"])
                    P.stt(xt[b][:, oc, :tw], ps[pb][:, :tw], gates[oc], xt[b][:, oc, :tw], ALU.mult, ALU.add,
                          [f"ps{pb}", f"xt{b}", "modv"], [f"xt{b}"])
                P.dma("sync", fm(xa[:, c0:c0 + tw]), xt[b][:, :, :tw], reads=[f"xt{b}"], writes=[("xa", i)], sem=f"xt{b}")


RW_ORDER1 = [(True, 0)] + [(False, i) for i in range(16)]
RW_ORDER2 = [(True, 0)] + [(False, i) for i in range(15, -1, -1)]


def rw_scratch(io):
    S = {}
    S["yp"] = io.scratch("rw_yp", [17, 8, 128, 256], F32)
    S["sadd"] = io.scratch("rw_sadd", [17, 8, 128, 256], F32)
    S["vst"] = io.scratch("rw_vst", [17, 8, 128, 256], F32)
    S["gst"] = io.scratch("rw_gst", [17, 8, 128, 256], F32)
    S["gyb"] = io.scratch("rw_gyb", [17, 8, 128, 512], BF16)
    S["gsb"] = io.scratch("rw_gsb", [17, 8, 128, 512], BF16)
    S["gamb"] = io.scratch("rw_gamb", [17, 128, 32], F32)
    S["bon"] = io.scratch("rw_bon", [17, 128, 32], F32)
    S["ops"] = io.scratch("rw_ops", [17, 8, 128, 2048], BF16)
    S["vb"] = io.scratch("rw_vb", [17, 8, 128, 256], BF16)
    S["gam"] = io.scratch("rw_gam", [17, 8, 128, 8], F32)
    return S


def stage_rwkv1(P, io, G, hp, S, dbg=None):
    vec, masks, identb, identf, bones, onesb, rmask = (G[k] for k in ("vec", "masks", "identb", "identf", "bones", "onesb", "rmask"))
    with P.phase("rwkv1"):
        wr = P.sb([128, 8, 1024], BF16)
        wk = P.sb([128, 8, 1024], BF16)
        wv = P.sb([128, 8, 1024], BF16)
        for w, nm in ((wr, "rwkv_wr"), (wk, "rwkv_wk"), (wv, "rwkv_wv")):
            P.dma("gpsimd", w[:], fm(io[nm]), writes=[nm], sem=nm)
        lw1 = P.sb([128, 8, 128], BF16)
        la1 = P.sb([128, 8, 128], BF16)
        g1 = P.sb([128, 8, 128], BF16)
        for d in range(2):
            P.dma("gpsimd", lw1[:, :, d * 64:(d + 1) * 64], io["rwkv_w1"][d].rearrange("(c p) j -> p c j", p=128), writes=["lw1"], sem=f"lw1{d}")
            P.dma("gpsimd", la1[:, :, d * 64:(d + 1) * 64], io["rwkv_a1"][d].rearrange("(c p) j -> p c j", p=128), writes=["la1"], sem=f"la1{d}")
        P.dma("gpsimd", g1[:], io["rwkv_g1"].rearrange("(c p) j -> p c j", p=128), writes=["g1"], sem="g1")
        w2s = P.sb([128, 1024], BF16)
        a2s = P.sb([128, 1024], BF16)
        g2 = P.sb([128, 1024], BF16)
        P.dma("gpsimd", w2s[:], io["rwkv_w2"].rearrange("d j f -> (d j) f"), writes=["w2s"], sem="w2s")
        P.dma("gpsimd", a2s[:], io["rwkv_a2"].rearrange("d j f -> (d j) f"), writes=["a2s"], sem="a2s")
        P.dma("gpsimd", g2[:], io["rwkv_g2"], writes=["g2"], sem="g2")

        hh = P.sb([128, 8, 384], F32)
        xx = P.sb([128, 8, 256], F32)
        xr = P.sb([128, 8, 256], BF16)
        xk = P.sb([128, 8, 256], BF16)
        xv = P.sb([128, 8, 256], BF16)
        xrot = P.sb([128, 8, 256], BF16)
        lwt = P.sb([128, 256], BF16)
        lat = P.sb([128, 256], BF16)
        sg = P.sb([128, 256], BF16)
        f32t = {}
        for nm in ("r", "k", "sw0", "sw1", "ag0", "ag1", "kq", "lnv", "rs", "kkn", "fac", "kd0", "kd1", "b0", "b1",
                   "L", "Lx", "Lb", "E1", "E2", "E3", "ks"):
            f32t[nm] = P.sb([128, 256], F32, "t_" + nm)
        sqb = P.sb([128, 256], BF16)
        RK = P.sb([128, 4, 2, 64], BF16)
        VTbd = P.sb([128, 4, 128], F32)
        GTbd = P.sb([128, 4, 128], F32)
        Vf = P.sb([128, 4, 64], F32)
        Gf = P.sb([128, 4, 64], F32)
        YPs = P.sb([128, 4, 64], F32)
        SAs = P.sb([128, 4, 64], F32)
        gamb_t = P.sb([128, 8, 4], F32)
        bon_t = P.sb([128, 8, 4], F32)
        Sf = P.sb([128, 8, 64], BF16)
        ARq = [[P.sb([128, 4, 2, 128], BF16, f"AR{q}{d}") for d in range(2)] for q in range(2)]
        KTq = [[P.sb([128, 4, 128], BF16, f"KT{q}{d}") for d in range(2)] for q in range(2)]
        BTq = [[P.sb([128, 4, 128], BF16, f"BT{q}{d}") for d in range(2)] for q in range(2)]
        Vbq = [P.sb([128, 4, 64], BF16, f"Vb{q}") for q in range(3)]
        gamq = [[P.sb([128, 4], F32, f"gam{q}{d}") for d in range(2)] for q in range(3)]
        inv = []
        for d in range(2):
            st = {}
            for nm, shp in (("Atok", [128, 4, 128]), ("Btok", [128, 4, 128]), ("MQ", [128, 4, 256]), ("MWa", [128, 4, 2, 128]),
                            ("MWb", [128, 4, 2, 128]), ("MTa", [128, 4, 128]), ("MTb", [128, 4, 128])):
                st[nm] = P.sb(shp, BF16, f"i{d}_{nm}")
            inv.append(st)
        fin = []
        for q in range(2):
            row = []
            for d in range(2):
                st = {}
                for nm, shp in (("Ktok", [128, 4, 128]), ("NP", [128, 4, 256]), ("XW", [128, 4, 256]), ("NVb", [128, 4, 64]),
                                ("GY", [128, 4, 128]), ("GS", [128, 4, 128])):
                    st[nm] = P.sb(shp, BF16, f"f{q}{d}_{nm}")
                row.append(st)
            fin.append(row)
        ppt = [P.ps([128, 512], F32) for _ in range(2)]
        pp = [t_[:, 0:256] for t_ in ppt]
        pf = P.ps([128, 512], F32)
        pb = [P.ps([128, 512], F32) for _ in range(5)]
        cnt = {"pp": 0, "pb": 0}
        nmod = {"pp": 2, "pb": 5}

        def nxt(kind):
            i = cnt[kind] % nmod[kind]
            cnt[kind] += 1
            return i

        for q in range(2):
            for d in range(2):
                P.memset("gpsimd", ARq[q][d][:], 0.0, [f"AR{q}{d}"])
                P.memset("gpsimd", KTq[q][d][:], 0.0, [f"KT{q}{d}"])
                P.memset("gpsimd", BTq[q][d][:], 0.0, [f"BT{q}{d}"])
        P.memset("gpsimd", RK[:], 0.0, ["RK"])
        P.memset("gpsimd", VTbd[:], 0.0, ["VTbd"])
        P.memset("gpsimd", GTbd[:], 0.0, ["GTbd"])
        P.memset("gpsimd", Sf[:], 0.0, [("Sf", p) for p in range(8)])

        def v3(ap):
            return ap.rearrange("p (u s) -> p u s", s=64)

        def u128(ap):
            return ap.rearrange("p (u x) -> p u x", x=128)

        def load_hh(ti):
            isctx, idx = RW_ORDER1[ti]
            off = 4288 if isctx else 64 + 256 * idx
            P.dma("sync", hh[:], fm(hp[:, off - 64: off + 320]), writes=["hh"], sem="hh")

        def proj8(w_cols_fn, xb, bn, extra_r):
            i = nxt("pp")
            for c in range(8):
                P.mm(pp[i], w_cols_fn(c), xb[:, c, :], c == 0, c == 7, [(bn, c)] + extra_r, [f"pp{i}"])
            return i

        def tprep(ti):
            isctx, idx = RW_ORDER1[ti]
            hc = hh[:, :, 64:320]
            XXW = [("xx", c) for c in range(8)]
            if not isctx:
                h4 = hh[:, :, 64:320].rearrange("p c (r w) -> p c r w", w=64)
                x4 = xx[:].rearrange("p c (r w) -> p c r w", w=64)
                P.tt("vector", x4[:, 0:2, :, 1:64], h4[:, 0:2, :, 0:63], h4[:, 0:2, :, 1:64], ALU.subtract, ["hh"], XXW[0:2])
                P.ts("gpsimd", x4[:, 0:2, :, 0:1], h4[:, 0:2, :, 0:1], -1.0, 0.0, ALU.mult, ALU.add, ["hh"], [("xxe", 0)])
                P.tt("vector", x4[:, 2:4, :, 0:63], h4[:, 2:4, :, 1:64], h4[:, 2:4, :, 0:63], ALU.subtract, ["hh"], XXW[2:4])
                P.ts("gpsimd", x4[:, 2:4, :, 63:64], h4[:, 2:4, :, 63:64], -1.0, 0.0, ALU.mult, ALU.add, ["hh"], [("xxe", 1)])
                P.tt("gpsimd", xx[:, 4:6, :], hh[:, 4:6, 0:256], hh[:, 4:6, 64:320], ALU.subtract, ["hh"], XXW[4:6])
                P.tt("gpsimd", xx[:, 6:8, :], hh[:, 6:8, 128:384], hh[:, 6:8, 64:320], ALU.subtract, ["hh"], XXW[6:8])
            else:
                P.tt("vector", xx[:, 0:4, :], hh[:, 0:4, 63:319], hh[:, 0:4, 64:320], ALU.subtract, ["hh"], XXW[0:4] + [("xxe", 0)])
                P.tt("gpsimd", xx[:, 4:8, :], hh[:, 4:8, 65:321], hh[:, 4:8, 64:320], ALU.subtract, ["hh"], XXW[4:8] + [("xxe", 1)])
            yield

            def mk_xj(j, buf, bn):
                for c in range(8):
                    P.stt(buf[:, c, :], xx[:, c, :], vec[:, 5 + j, c:c + 1], hc[:, c, :], ALU.mult, ALU.add,
                          [("xx", c), ("xxe", 0), ("xxe", 1), "hh", "vec"], [(bn, c)])

            mk_xj(1, xrot, "xrot")
            yield
            i = proj8(lambda c: lw1[:, c, :], xrot, "xrot", ["lw1"])
            P.act(lwt[:], pp[i], AF.Tanh, [f"pp{i}"], ["lwt"])
            yield
            mk_xj(4, xrot, "xrot")
            yield
            i = proj8(lambda c: la1[:, c, :], xrot, "xrot", ["la1"])
            P.cp("scalar", lat[:], pp[i], [f"pp{i}"], ["lat"])
            yield
            mk_xj(5, xrot, "xrot")
            yield
            i = proj8(lambda c: g1[:, c, :], xrot, "xrot", ["g1"])
            P.act(sg[:], pp[i], AF.Sigmoid, [f"pp{i}"], ["sg"])
            yield
            mk_xj(0, xr, "xr")
            yield
            mk_xj(2, xk, "xk")
            yield
            mk_xj(3, xv, "xv")
            if ti + 1 < len(RW_ORDER1):
                load_hh(ti + 1)
            yield

        def prep(ti, oc, q, z):
            isctx, idx = RW_ORDER1[ti]
            tg = 16 if isctx else idx
            cs = slice(oc * 128, (oc + 1) * 128)
            t = f32t
            AR, KT, BT, Vb, gam = ARq[q], KTq[q], BTq[q], Vbq[z], gamq[z]
            i = proj8(lambda c: wr[:, c, cs], xr, "xr", ["rwkv_wr"])
            P.cp("scalar", t["r"][:], pp[i], [f"pp{i}"], ["r"])
            i = proj8(lambda c: wk[:, c, cs], xk, "xk", ["rwkv_wk"])
            P.cp("scalar", t["k"][:], pp[i], [f"pp{i}"], ["k"])
            i = proj8(lambda c: wv[:, c, cs], xv, "xv", ["rwkv_wv"])
            vt4 = VTbd[:].rearrange("p u (h s) -> p u h s", h=2)
            for h2 in range(2):
                sl = slice(h2 * 64, (h2 + 1) * 64)
                P.cp("scalar", vt4[sl, :, h2, :], v3(pp[i][sl, :]), [f"pp{i}"], ["VTbd"])
            i = nxt("pp")
            P.mm(pp[i], g2[:, cs], sg[:], True, True, ["g2", "sg"], [f"pp{i}"])
            gt4 = GTbd[:].rearrange("p u (h s) -> p u h s", h=2)
            for h2 in range(2):
                sl = slice(h2 * 64, (h2 + 1) * 64)
                P.cp("scalar", gt4[sl, :, h2, :], v3(pp[i][sl, :]), [f"pp{i}"], ["GTbd"])
            yield
            j = nxt("pb")
            for u in range(4):
                P.tr(pb[j][:, u * 128:(u + 1) * 128], VTbd[:, u, :], identf[:], ["VTbd", "identf"], [f"pb{j}"])
            pv = u128(pb[j][:])
            for h2 in range(2):
                sl = slice(h2 * 64, (h2 + 1) * 64)
                P.cp("scalar", Vf[sl, :, :], pv[sl, :, h2 * 64:(h2 + 1) * 64], [f"pb{j}"], ["Vf"])
            P.cp("gpsimd", Vb[:], Vf[:], ["Vf"], [f"Vb{z}"])
            P.dma("sync", S["vst"][tg, oc].rearrange("p (u s) -> p u s", s=64), Vf[:], reads=["Vf"], writes=[("vst", tg, oc)], sem="Vf")
            j = nxt("pb")
            for u in range(4):
                P.tr(pb[j][:, u * 128:(u + 1) * 128], GTbd[:, u, :], identf[:], ["GTbd", "identf"], [f"pb{j}"])
            pv = u128(pb[j][:])
            for h2 in range(2):
                sl = slice(h2 * 64, (h2 + 1) * 64)
                P.cp("scalar", Gf[sl, :, :], pv[sl, :, h2 * 64:(h2 + 1) * 64], [f"pb{j}"], ["Gf"])
            P.dma("sync", S["gst"][tg, oc].rearrange("p (u s) -> p u s", s=64), Gf[:], reads=["Gf"], writes=[("gst", tg, oc)], sem="Gf")
            yield
            for d in range(2):
                dl = slice(d * 64, (d + 1) * 64)
                i = nxt("pp")
                P.mm(pp[i], w2s[dl, cs], lwt[dl, :], True, True, ["w2s", "lwt"], [f"pp{i}"])
                P.act(t[f"sw{d}"][:], pp[i], AF.Sigmoid, [f"pp{i}", "vec"], [f"sw{d}"], bias=vec[:, 11 + d, oc:oc + 1])
                i = nxt("pp")
                P.mm(pp[i], a2s[dl, cs], lat[dl, :], True, True, ["a2s", "lat"], [f"pp{i}"])
                P.act(t[f"ag{d}"][:], pp[i], AF.Sigmoid, [f"pp{i}", "vec"], [f"ag{d}"], bias=vec[:, 13 + d, oc:oc + 1])
            yield
            P.ts("vector", t["kq"][:], t["k"][:], vec[:, 15, oc:oc + 1], None, ALU.mult, None, ["k", "vec"], ["kq"])
            P.act(sqb[:], t["kq"][:], AF.Square, ["kq"], ["sqb"])
            i = nxt("pp")
            P.mm(pp[i], bones[:], sqb[:], True, True, ["bones", "sqb"], [f"pp{i}"])
            P.act(t["lnv"][:], pp[i], AF.Ln, [f"pp{i}"], ["lnv"], bias=1e-12)
            P.act(t["rs"][:], t["lnv"][:], AF.Exp, ["lnv"], ["rs"], scale=-0.5)
            P.tt("gpsimd", t["kkn"][:], t["kq"][:], t["rs"][:], ALU.mult, ["kq", "rs"], ["kkn"])
            for d in range(2):
                sw, ag, kd, bb = t[f"sw{d}"], t[f"ag{d}"], t[f"kd{d}"], t[f"b{d}"]
                EE = "gpsimd" if d == 0 else "vector"
                P.ts(EE, t["fac"][:], ag[:], vec[:, 16, oc:oc + 1], vec[:, 17, oc:oc + 1], ALU.mult, ALU.add, [f"ag{d}", "vec"], ["fac"])
                P.tt(EE, kd[:], t["k"][:], t["fac"][:], ALU.mult, ["k", "fac"], [f"kd{d}"])
                P.tt(EE, bb[:], t["kkn"][:], ag[:], ALU.mult, ["kkn", f"ag{d}"], [f"b{d}"])
                P.op("vector", lambda e, sw=sw: e.tensor_tensor_scan(out=t["L"][:], data0=rmask[:], data1=sw[:], initial=0.0,
                                                                      op0=ALU.mult, op1=ALU.add), [f"sw{d}", "rmask"], ["L"])
                L3 = v3(t["L"][:])
                if d == 0:
                    P.tt(EE, t["Lx"][:], t["L"][:], sw[:], ALU.subtract, ["L", f"sw{d}"], ["Lx"])
                    Li, Lin = t["L"], "L"
                else:
                    P.tt(EE, v3(t["Lx"][:]), L3[:, :, 63:64].broadcast_to([128, 4, 64]), L3, ALU.subtract, ["L"], ["Lx"])
                    P.tt(EE, t["Lb"][:], t["Lx"][:], sw[:], ALU.add, ["Lx", f"sw{d}"], ["Lb"])
                    Li, Lin = t["Lb"], "Lb"
                P.act(t["E1"][:], Li[:], AF.Exp, [Lin], ["E1"], scale=-C0)
                P.act(t["E3"][:], Li[:], AF.Exp, [Lin], ["E3"], scale=C0)
                P.act(t["E2"][:], t["Lx"][:], AF.Exp, ["Lx"], ["E2"], scale=-C0)
                ar5 = AR[d][:].rearrange("p u a (h s) -> p u a h s", h=2)
                kt4 = KT[d][:].rearrange("p u (h s) -> p u h s", h=2)
                bt4 = BT[d][:].rearrange("p u (h s) -> p u h s", h=2)
                for h2 in range(2):
                    sl = slice(h2 * 64, (h2 + 1) * 64)
                    P.stt(ar5[sl, :, 0, h2, :], v3(t["kkn"][sl, :]), -1.0, v3(t["E2"][sl, :]), ALU.mult, ALU.mult, ["kkn", "E2"], [f"AR{q}{d}"])
                    P.tt(EE, ar5[sl, :, 1, h2, :], v3(t["r"][sl, :]), v3(t["E1"][sl, :]), ALU.mult, ["r", "E1"], [f"AR{q}{d}"])
                    P.tt(EE, kt4[sl, :, h2, :], v3(kd[sl, :]), v3(t["E3"][sl, :]), ALU.mult, [f"kd{d}", "E3"], [f"KT{q}{d}"])
                    P.tt(EE, bt4[sl, :, h2, :], v3(bb[sl, :]), v3(t["E3"][sl, :]), ALU.mult, [f"b{d}", "E3"], [f"BT{q}{d}"])
                E13 = v3(t["E1"][:])
                gsrc = E13[:, :, 63] if d == 0 else E13[:, :, 0]
                P.cp("vector", gam[d][:], gsrc, ["E1"], [f"gam{z}{d}"])
                if d == 1:
                    P.cp("gpsimd", gamb_t[:, oc, :], gam[1][:], [f"gam{z}1"], ["gamb_t"])
                yield
            P.tt("gpsimd", t["ks"][:], t["kd0"][:], t["kd1"][:], ALU.add, ["kd0", "kd1"], ["ks"])
            for h2 in range(2):
                sl = slice(h2 * 64, (h2 + 1) * 64)
                P.stt(RK[sl, :, h2, :], v3(t["r"][sl, :]), vec[sl, 18, oc:oc + 1], v3(t["ks"][sl, :]), ALU.mult, ALU.mult, ["r", "ks", "vec"], ["RK"])
            i = nxt("pp")
            for u in range(4):
                P.mm(pp[i][:, u:u + 1], RK[:, u, :, :].rearrange("p h s -> p (h s)"), onesb[:, 0:1], True, True, ["RK", "onesb"], [f"pp{i}"])
            P.cp("scalar", bon_t[:, oc, :], pp[i][:, 0:4], [f"pp{i}"], ["bon_t"])
            if oc == 7:
                P.dma("sync", S["gamb"][tg], gamb_t[:].rearrange("p a b -> p (a b)"), reads=["gamb_t"], writes=[("gamb", tg)], sem="gamb_t")
                P.dma("sync", S["bon"][tg], bon_t[:].rearrange("p a b -> p (a b)"), reads=["bon_t"], writes=[("bon", tg)], sem="bon_t")
            yield

        def chain(ti, oc, q, d, z):
            AR, KT, BT, Vb = ARq[q][d], KTq[q][d], BTq[q][d], Vbq[z]
            ARn, KTn, BTn, Vbn = f"AR{q}{d}", f"KT{q}{d}", f"BT{q}{d}", f"Vb{z}"
            iv, fn = inv[d], fin[q][d]
            IR = lambda nm: f"i{d}_{nm}"
            FR = lambda nm: f"f{q}{d}_{nm}"
            mS, mC = (0, 2) if d == 0 else (2, 0)
            mSI = masks[:, mS:mS + 2, :].rearrange("p a b -> p (a b)").unsqueeze(1).broadcast_to([128, 4, 256])
            mCb = masks[:, mC, :].unsqueeze(1).broadcast_to([128, 4, 128])
            idb = identb[:].unsqueeze(1).broadcast_to([128, 4, 128])
            for src, srcn, dst, dstn in ((AR[:, :, 0, :], ARn, iv["Atok"], IR("Atok")), (BT[:], BTn, iv["Btok"], IR("Btok")),
                                         (KT[:], KTn, fn["Ktok"], FR("Ktok"))):
                j = nxt("pb")
                pbt = pb[j][:].bitcast(BF16)
                for u in range(4):
                    P.tr(pbt[:, u * 128:(u + 1) * 128], src[:, u, :], identb[:], [srcn, "identb"], [f"pb{j}"])
                P.cp("scalar", dst[:].rearrange("p u x -> p (u x)"), pbt[:, 0:512], [f"pb{j}"], [dstn])
            mSb = masks[:, mS, :].unsqueeze(1).broadcast_to([128, 4, 128])
            mIb = masks[:, mS + 1, :].unsqueeze(1).broadcast_to([128, 4, 128])

            def two_bank(mm_fn):
                j0, j1 = nxt("pb"), nxt("pb")
                for u in range(4):
                    mm_fn(u, pb[j0][:, u * 128:(u + 1) * 128], f"pb{j0}", pb[j1][:, u * 128:(u + 1) * 128], f"pb{j1}")
                return j0, j1

            for lhs, lhsn, dst, dstn in ((BT, BTn, iv["MQ"], IR("MQ")), (KT, KTn, fn["NP"], FR("NP"))):
                def mm_ab(u, o0, n0, o1, n1, lhs=lhs, lhsn=lhsn):
                    P.mm(o0, lhs[:, u, :], AR[:, u, 0, :], True, True, [lhsn, ARn], [n0])
                    P.mm(o1, lhs[:, u, :], AR[:, u, 1, :], True, True, [lhsn, ARn], [n1])
                j0, j1 = two_bank(mm_ab)
                P.tt("vector", dst[:, :, 0:128], u128(pb[j0][:]), mSb, ALU.mult, [f"pb{j0}", "masks"], [dstn])
                P.tt("vector", dst[:, :, 128:256], u128(pb[j1][:]), mIb, ALU.mult, [f"pb{j1}", "masks"], [dstn])
            j = nxt("pb")
            for u in range(4):
                P.mm(pb[j][:, u * 128:(u + 1) * 128], AR[:, u, 0, :], BT[:, u, :], True, True, [ARn, BTn], [f"pb{j}"])
            cur, curn, nx, nxn = iv["MWa"], IR("MWa"), iv["MWb"], IR("MWb")
            P.tt("vector", cur[:, :, 0, :], u128(pb[j][:]), mCb, ALU.mult, [f"pb{j}", "masks"], [curn])
            yield
            j = nxt("pb")
            for u in range(4):
                P.mm(pb[j][:, u * 128:(u + 1) * 128], iv["MQ"][:, u, 0:128], cur[:, u, 0, :], True, True, [IR("MQ"), curn], [f"pb{j}"])
            P.cp("scalar", nx[:, :, 0, :], u128(pb[j][:]), [f"pb{j}"], [nxn])
            P.tt("gpsimd", nx[:, :, 1, :], cur[:, :, 0, :], idb, ALU.add, [curn, "identb"], [nxn])
            j = nxt("pb")
            for u in range(4):
                P.mm(pb[j][:, u * 128:(u + 1) * 128], cur[:, u, 0, :], iv["MQ"][:, u, 0:128], True, True, [IR("MQ"), curn], [f"pb{j}"])
            curT, curTn, nxT, nxTn = iv["MTa"], IR("MTa"), iv["MTb"], IR("MTb")
            P.cp("scalar", curT[:], u128(pb[j][:]), [f"pb{j}"], [curTn])
            cur, curn, nx, nxn = nx, nxn, cur, curn
            yield
            for lev in range(1, 5):
                def mm_lev(u, o0, n0, o1, n1, cur=cur, curn=curn, curT=curT, curTn=curTn):
                    P.mm(o0, curT[:, u, :], cur[:, u, 0, :], True, True, [curTn, curn], [n0])
                    P.mm(o1, curT[:, u, :], cur[:, u, 1, :], True, True, [curTn, curn], [n1])
                j0, j1 = two_bank(mm_lev)
                P.cp("scalar", nx[:, :, 0, :], u128(pb[j0][:]), [f"pb{j0}"], [nxn])
                P.tt("vector", nx[:, :, 1, :], u128(pb[j1][:]), cur[:, :, 1, :], ALU.add, [f"pb{j1}", curn], [nxn])
                j = nxt("pb")
                for u in range(4):
                    P.mm(pb[j][:, u * 128:(u + 1) * 128], cur[:, u, 0, :], curT[:, u, :], True, True, [curn, curTn], [f"pb{j}"])
                P.cp("scalar", nxT[:], u128(pb[j][:]), [f"pb{j}"], [nxTn])
                cur, curn, nx, nxn = nx, nxn, cur, curn
                curT, curTn, nxT, nxTn = nxT, nxTn, curT, curTn
                yield
            j = nxt("pb")
            for u in range(4):
                P.mm(pb[j][:, u * 128:(u + 1) * 128], curT[:, u, :], cur[:, u, 1, :], True, True, [curTn, curn], [f"pb{j}"])
            P.tt("vector", nx[:, :, 1, :], u128(pb[j][:]), cur[:, :, 1, :], ALU.add, [f"pb{j}", curn], [nxn])
            W6, W6n = nx, nxn
            j = nxt("pb")
            for u in range(4):
                P.mm(pb[j][:, u * 64:(u + 1) * 64], fn["NP"][:, u, 0:128], Vb[:, u, :], True, True, [FR("NP"), Vbn], [f"pb{j}"])
            P.cp("scalar", fn["NVb"][:].rearrange("p u x -> p (u x)"), pb[j][:, 0:256], [f"pb{j}"], [FR("NVb")])
            yield

            def mm_d(u, o0, n0, o1, n1):
                P.mm(o0, W6[:, u, 1, :], iv["MQ"][:, u, 128:256], True, True, [W6n, IR("MQ")], [n0])
                P.mm(o1, W6[:, u, 1, :], iv["Btok"][:, u, :], True, True, [W6n, IR("Btok")], [n1])
            j0, j1 = two_bank(mm_d)
            P.cp("scalar", fn["XW"][:, :, 0:128], u128(pb[j0][:]), [f"pb{j0}"], [FR("XW")])
            P.cp("vector", fn["XW"][:, :, 128:256], u128(pb[j1][:]), [f"pb{j1}"], [FR("XW")])
            yield

            def mm_f(u, o0, n0, o1, n1):
                P.mm(o0, iv["Atok"][:, u, :], fn["XW"][:, u, 0:128], True, True, [IR("Atok"), FR("XW")], [n0])
                P.mm(o1, iv["Atok"][:, u, :], fn["XW"][:, u, 128:256], True, True, [IR("Atok"), FR("XW")], [n1])
            j0, j1 = two_bank(mm_f)
            P.tt("vector", fn["GY"][:], u128(pb[j0][:]), AR[:, :, 1, :], ALU.add, [f"pb{j0}", ARn], [FR("GY")])
            P.tt("vector", fn["GS"][:], u128(pb[j1][:]), idb, ALU.add, [f"pb{j1}", "identb"], [FR("GS")])
            yield

        def finish(ti, oc, q, z):
            isctx, idx = RW_ORDER1[ti]
            tg = 16 if isctx else idx
            sf, sb_ = fin[q]
            F0 = lambda nm: f"f{q}0_{nm}"
            F1 = lambda nm: f"f{q}1_{nm}"
            Vb, Vbn, gam = Vbq[z], f"Vb{z}", gamq[z]
            SFR = ("Sf", oc)
            for u in range(4):
                yo = pf[:, u * 64:(u + 1) * 64]
                P.mm(yo, sf["NP"][:, u, 128:256], Vb[:, u, :], True, False, [F0("NP"), Vbn], ["pf"])
                P.mm(yo, sf["XW"][:, u, 0:128], sf["NVb"][:, u, :], False, False, [F0("XW"), F0("NVb")], ["pf"])
                P.mm(yo, sb_["NP"][:, u, 128:256], Vb[:, u, :], False, False, [F1("NP"), Vbn], ["pf"])
                P.mm(yo, sb_["XW"][:, u, 0:128], sb_["NVb"][:, u, :], False, False, [F1("XW"), F1("NVb")], ["pf"])
                P.mm(yo, sf["GY"][:, u, :], Sf[:, oc, :], False, True, [F0("GY"), SFR], ["pf"])
                so = pf[:, 256:320]
                P.mm(so, sf["Ktok"][:, u, :], Vb[:, u, :], True, False, [F0("Ktok"), Vbn], ["pf"])
                P.mm(so, sf["XW"][:, u, 128:256], sf["NVb"][:, u, :], False, False, [F0("XW"), F0("NVb")], ["pf"])
                P.mm(so, sf["GS"][:, u, :], Sf[:, oc, :], False, True, [F0("GS"), SFR], ["pf"])
                P.ts("vector", Sf[:, oc, :], so, gam[0][:, u:u + 1], None, ALU.mult, None, ["pf", f"gam{z}0"], [SFR])
                yield
            P.cp("vector", YPs[:].rearrange("p u x -> p (u x)"), pf[:, 0:256], ["pf"], ["YPs"])
            P.dma("sync", S["yp"][tg, oc], YPs[:].rearrange("p u x -> p (u x)"), reads=["YPs"], writes=[("yp", tg, oc)], sem="YPs")
            j = nxt("pb")
            for u in range(4):
                so = pb[j][:, u * 64:(u + 1) * 64]
                P.mm(so, sb_["Ktok"][:, u, :], Vb[:, u, :], True, False, [F1("Ktok"), Vbn], [f"pb{j}"])
                P.mm(so, sb_["XW"][:, u, 128:256], sb_["NVb"][:, u, :], False, True, [F1("XW"), F1("NVb")], [f"pb{j}"])
            P.cp("scalar", SAs[:].rearrange("p u x -> p (u x)"), pb[j][:, 0:256], [f"pb{j}"], ["SAs"])
            P.dma("sync", S["sadd"][tg, oc], SAs[:].rearrange("p u x -> p (u x)"), reads=["SAs"], writes=[("sadd", tg, oc)], sem="SAs")
            P.dma("sync", S["gyb"][tg, oc], sb_["GY"][:].rearrange("p u x -> p (u x)"), reads=[F1("GY")], writes=[("gyb", tg, oc)], sem=F1("GY"))
            P.dma("sync", S["gsb"][tg, oc], sb_["GS"][:].rearrange("p u x -> p (u x)"), reads=[F1("GS")], writes=[("gsb", tg, oc)], sem=F1("GS"))
            yield

        NT = len(RW_ORDER1)
        NJ = NT * 8
        done = {"prep": set(), "c0": set(), "c1": set(), "fin": set(), "tprep": set()}

        def stream_P():
            for ti in range(NT):
                yield ("tprep", ti, lambda ti=ti: (ti == 0 or ("prep", (ti - 1) * 8 + 7) in donef), lambda ti=ti: tprep(ti))
                for oc in range(8):
                    k = ti * 8 + oc
                    yield ("prep", k, lambda k=k: ((k < 2 or (("c0", k - 2) in donef and ("c1", k - 2) in donef)) and (k < 3 or ("fin", k - 3) in donef)),
                           lambda ti=ti, oc=oc, k=k: prep(ti, oc, k % 2, k % 3))

        def stream_C(d):
            for k in range(NJ):
                ti, oc = divmod(k, 8)
                yield (f"c{d}", k, lambda k=k: (("prep", k) in donef and (k < 2 or ("fin", k - 2) in donef)),
                       lambda ti=ti, oc=oc, k=k: chain(ti, oc, k % 2, d, k % 3))

        def stream_F():
            for k in range(NJ):
                ti, oc = divmod(k, 8)
                yield ("fin", k, lambda k=k: (("c0", k) in donef and ("c1", k) in donef),
                       lambda ti=ti, oc=oc, k=k: finish(ti, oc, k % 2, k % 3))

        donef = set()
        load_hh(0)
        streams = [stream_C(0), stream_C(1), stream_F(), stream_P()]
        cur = [None] * 4
        pend = [None] * 4
        alive = [True] * 4
        while any(alive):
            progressed = False
            for si in range(4):
                if not alive[si]:
                    continue
                if cur[si] is None:
                    if pend[si] is None:
                        try:
                            pend[si] = next(streams[si])
                        except StopIteration:
                            alive[si] = False
                            continue
                    kind, k, ready, mk = pend[si]
                    if not ready():
                        continue
                    cur[si] = (kind, k, mk())
                    pend[si] = None
                kind, k, gen = cur[si]
                try:
                    next(gen)
                    progressed = True
                except StopIteration:
                    donef.add((kind, k))
                    cur[si] = None
                    progressed = True
            assert progressed or not any(alive), "scheduler stuck"


def stage_rwkv1a(P, io, G, hp, S):
    vec, masks, identb, identf, bones, onesb, rmask = (G[k] for k in ("vec", "masks", "identb", "identf", "bones", "onesb", "rmask"))
    with P.phase("rwkv1a"):
        wr = P.sb([128, 8, 1024], BF16)
        wk = P.sb([128, 8, 1024], BF16)
        wv = P.sb([128, 8, 1024], BF16)
        for w, nm in ((wr, "rwkv_wr"), (wk, "rwkv_wk"), (wv, "rwkv_wv")):
            P.dma("gpsimd", w[:], fm(io[nm]), writes=[nm], sem=nm)
        lw1 = P.sb([128, 8, 128], BF16)
        la1 = P.sb([128, 8, 128], BF16)
        g1 = P.sb([128, 8, 128], BF16)
        for d in range(2):
            P.dma("gpsimd", lw1[:, :, d * 64:(d + 1) * 64], io["rwkv_w1"][d].rearrange("(c p) j -> p c j", p=128), writes=["lw1"], sem=f"lw1{d}")
            P.dma("gpsimd", la1[:, :, d * 64:(d + 1) * 64], io["rwkv_a1"][d].rearrange("(c p) j -> p c j", p=128), writes=["la1"], sem=f"la1{d}")
        P.dma("gpsimd", g1[:], io["rwkv_g1"].rearrange("(c p) j -> p c j", p=128), writes=["g1"], sem="g1")
        w2s = P.sb([128, 1024], BF16)
        a2s = P.sb([128, 1024], BF16)
        g2 = P.sb([128, 1024], BF16)
        P.dma("gpsimd", w2s[:], io["rwkv_w2"].rearrange("d j f -> (d j) f"), writes=["w2s"], sem="w2s")
        P.dma("gpsimd", a2s[:], io["rwkv_a2"].rearrange("d j f -> (d j) f"), writes=["a2s"], sem="a2s")
        P.dma("gpsimd", g2[:], io["rwkv_g2"], writes=["g2"], sem="g2")

        hh = P.sb([128, 8, 384], F32)
        xx = P.sb([128, 8, 256], F32)
        xr = P.sb([128, 8, 256], BF16)
        xk = P.sb([128, 8, 256], BF16)
        xv = P.sb([128, 8, 256], BF16)
        xrot = P.sb([128, 8, 256], BF16)
        lwt = P.sb([128, 256], BF16)
        lat = P.sb([128, 256], BF16)
        sg = P.sb([128, 256], BF16)
        NSET = 3
        bufs = []
        for w_ in range(NSET):
            B_ = {"t": {}}
            for nm in ("r", "k", "sw0", "sw1", "ag0", "ag1", "kq", "lnv", "rs", "kkn", "fac", "kd0", "kd1", "b0", "b1",
                       "L", "Lx", "Lb", "E1", "E2", "E3", "ks"):
                B_["t"][nm] = P.sb([128, 256], F32, f"t{w_}_" + nm)
            B_["sqb"] = P.sb([128, 256], BF16)
            B_["RK"] = P.sb([128, 4, 2, 64], BF16)
            B_["VTbd"] = P.sb([128, 4, 128], F32)
            B_["GTbd"] = P.sb([128, 4, 128], F32)
            B_["Vf"] = P.sb([128, 4, 64], F32)
            B_["Gf"] = P.sb([128, 4, 64], F32)
            B_["ops"] = P.sb([128, 2, 4, 256], BF16)
            B_["vb"] = P.sb([128, 4, 64], BF16)
            B_["gam"] = P.sb([128, 2, 4], F32)
            bufs.append(B_)
            P.memset("gpsimd", B_["RK"][:], 0.0, [("RK", w_)])
            P.memset("gpsimd", B_["VTbd"][:], 0.0, [("VTbd", w_)])
            P.memset("gpsimd", B_["GTbd"][:], 0.0, [("GTbd", w_)])
        P.ns_set = frozenset(["r", "k", "sw0", "sw1", "ag0", "ag1", "kq", "lnv", "rs", "kkn", "fac", "kd0", "kd1", "b0", "b1",
                              "L", "Lx", "Lb", "E1", "E2", "E3", "ks", "sqb", "RK", "VTbd", "GTbd", "Vf", "Gf", "ops_st", "vb_st", "gam_st"])
        gamb_t = P.sb([128, 8, 4], F32)
        bon_t = P.sb([128, 8, 4], F32)
        ppt = [P.ps([128, 512], F32) for _ in range(4)]
        pp = [t_[:, 0:256] for t_ in ppt]
        pb = [P.ps([128, 512], F32) for _ in range(4)]
        cnt = {"pp": 0, "pb": 0}
        nmod = {"pp": 4, "pb": 4}

        def nxt(kind):
            i = cnt[kind] % nmod[kind]
            cnt[kind] += 1
            return i

        def v3(ap):
            return ap.rearrange("p (u s) -> p u s", s=64)

        def u128(ap):
            return ap.rearrange("p (u x) -> p u x", x=128)

        def load_hh(ti):
            isctx, idx = RW_ORDER1[ti]
            off = 4288 if isctx else 64 + 256 * idx
            P.dma("sync", hh[:], fm(hp[:, off - 64: off + 320]), writes=["hh"], sem="hh")

        def proj8(w_cols_fn, xb, bn, extra_r):
            i = nxt("pp")
            for c in range(8):
                P.mm(pp[i], w_cols_fn(c), xb[:, c, :], c == 0, c == 7, [(bn, c)] + extra_r, [f"pp{i}"])
            return i

        def tprep(ti):
            isctx, idx = RW_ORDER1[ti]
            hc = hh[:, :, 64:320]
            XXW = [("xx", c) for c in range(8)]
            if not isctx:
                h4 = hh[:, :, 64:320].rearrange("p c (r w) -> p c r w", w=64)
                x4 = xx[:].rearrange("p c (r w) -> p c r w", w=64)
                P.tt("vector", x4[:, 0:2, :, 1:64], h4[:, 0:2, :, 0:63], h4[:, 0:2, :, 1:64], ALU.subtract, ["hh"], XXW[0:2])
                P.ts("gpsimd", x4[:, 0:2, :, 0:1], h4[:, 0:2, :, 0:1], -1.0, 0.0, ALU.mult, ALU.add, ["hh"], [("xxe", 0)])
                P.tt("vector", x4[:, 2:4, :, 0:63], h4[:, 2:4, :, 1:64], h4[:, 2:4, :, 0:63], ALU.subtract, ["hh"], XXW[2:4])
                P.ts("gpsimd", x4[:, 2:4, :, 63:64], h4[:, 2:4, :, 63:64], -1.0, 0.0, ALU.mult, ALU.add, ["hh"], [("xxe", 1)])
                P.tt("gpsimd", xx[:, 4:6, :], hh[:, 4:6, 0:256], hh[:, 4:6, 64:320], ALU.subtract, ["hh"], XXW[4:6])
                P.tt("gpsimd", xx[:, 6:8, :], hh[:, 6:8, 128:384], hh[:, 6:8, 64:320], ALU.subtract, ["hh"], XXW[6:8])
            else:
                P.tt("vector", xx[:, 0:4, :], hh[:, 0:4, 63:319], hh[:, 0:4, 64:320], ALU.subtract, ["hh"], XXW[0:4] + [("xxe", 0)])
                P.tt("gpsimd", xx[:, 4:8, :], hh[:, 4:8, 65:321], hh[:, 4:8, 64:320], ALU.subtract, ["hh"], XXW[4:8] + [("xxe", 1)])
            yield

            def mk_xj(j, buf, bn):
                for c in range(8):
                    P.stt(buf[:, c, :], xx[:, c, :], vec[:, 5 + j, c:c + 1], hc[:, c, :], ALU.mult, ALU.add,
                          [("xx", c), ("xxe", 0), ("xxe", 1), "hh", "vec"], [(bn, c)])

            mk_xj(1, xrot, "xrot")
            yield
            i = proj8(lambda c: lw1[:, c, :], xrot, "xrot", ["lw1"])
            P.act(lwt[:], pp[i], AF.Tanh, [f"pp{i}"], ["lwt"])
            yield
            mk_xj(4, xrot, "xrot")
            yield
            i = proj8(lambda c: la1[:, c, :], xrot, "xrot", ["la1"])
            P.cp("scalar", lat[:], pp[i], [f"pp{i}"], ["lat"])
            yield
            mk_xj(5, xrot, "xrot")
            yield
            i = proj8(lambda c: g1[:, c, :], xrot, "xrot", ["g1"])
            P.act(sg[:], pp[i], AF.Sigmoid, [f"pp{i}"], ["sg"])
            yield
            mk_xj(0, xr, "xr")
            yield
            mk_xj(2, xk, "xk")
            yield
            mk_xj(3, xv, "xv")
            if ti + 1 < len(RW_ORDER1):
                load_hh(ti + 1)
            yield

        def prep(ti, oc, w):
            isctx, idx = RW_ORDER1[ti]
            tg = 16 if isctx else idx
            cs = slice(oc * 128, (oc + 1) * 128)
            B_ = bufs[w]
            t, sqb, RK, VTbd, GTbd, Vf, Gf = B_["t"], B_["sqb"], B_["RK"], B_["VTbd"], B_["GTbd"], B_["Vf"], B_["Gf"]
            ops_st, vb_st, gam_st = B_["ops"], B_["vb"], B_["gam"]
            i = proj8(lambda c: wr[:, c, cs], xr, "xr", ["rwkv_wr"])
            P.cp("scalar", t["r"][:], pp[i], [f"pp{i}"], ["r"])
            i = proj8(lambda c: wk[:, c, cs], xk, "xk", ["rwkv_wk"])
            P.cp("scalar", t["k"][:], pp[i], [f"pp{i}"], ["k"])
            yield
            i = proj8(lambda c: wv[:, c, cs], xv, "xv", ["rwkv_wv"])
            vt4 = VTbd[:].rearrange("p u (h s) -> p u h s", h=2)
            for h2 in range(2):
                sl = slice(h2 * 64, (h2 + 1) * 64)
                P.cp("scalar", vt4[sl, :, h2, :], v3(pp[i][sl, :]), [f"pp{i}"], ["VTbd"])
            i = nxt("pp")
            P.mm(pp[i], g2[:, cs], sg[:], True, True, ["g2", "sg"], [f"pp{i}"])
            gt4 = GTbd[:].rearrange("p u (h s) -> p u h s", h=2)
            for h2 in range(2):
                sl = slice(h2 * 64, (h2 + 1) * 64)
                P.cp("scalar", gt4[sl, :, h2, :], v3(pp[i][sl, :]), [f"pp{i}"], ["GTbd"])
            yield
            j = nxt("pb")
            for u in range(4):
                P.tr(pb[j][:, u * 128:(u + 1) * 128], VTbd[:, u, :], identf[:], ["VTbd", "identf"], [f"pb{j}"])
            pv = u128(pb[j][:])
            for h2 in range(2):
                sl = slice(h2 * 64, (h2 + 1) * 64)
                P.cp("scalar", Vf[sl, :, :], pv[sl, :, h2 * 64:(h2 + 1) * 64], [f"pb{j}"], ["Vf"])
            P.cp("gpsimd", vb_st[:], Vf[:], ["Vf"], ["vb_st"])
            P.dma("sync", S["vst"][tg, oc].rearrange("p (u s) -> p u s", s=64), Vf[:], reads=["Vf"], writes=[("vst", tg, oc)], sem="Vf")
            yield
            j = nxt("pb")
            for u in range(4):
                P.tr(pb[j][:, u * 128:(u + 1) * 128], GTbd[:, u, :], identf[:], ["GTbd", "identf"], [f"pb{j}"])
            pv = u128(pb[j][:])
            for h2 in range(2):
                sl = slice(h2 * 64, (h2 + 1) * 64)
                P.cp("scalar", Gf[sl, :, :], pv[sl, :, h2 * 64:(h2 + 1) * 64], [f"pb{j}"], ["Gf"])
            P.dma("sync", S["gst"][tg, oc].rearrange("p (u s) -> p u s", s=64), Gf[:], reads=["Gf"], writes=[("gst", tg, oc)], sem="Gf")
            yield
            for d in range(2):
                dl = slice(d * 64, (d + 1) * 64)
                i = nxt("pp")
                P.mm(pp[i], w2s[dl, cs], lwt[dl, :], True, True, ["w2s", "lwt"], [f"pp{i}"])
                P.act(t[f"sw{d}"][:], pp[i], AF.Sigmoid, [f"pp{i}", "vec"], [f"sw{d}"], bias=vec[:, 11 + d, oc:oc + 1])
                i = nxt("pp")
                P.mm(pp[i], a2s[dl, cs], lat[dl, :], True, True, ["a2s", "lat"], [f"pp{i}"])
                P.act(t[f"ag{d}"][:], pp[i], AF.Sigmoid, [f"pp{i}", "vec"], [f"ag{d}"], bias=vec[:, 13 + d, oc:oc + 1])
            yield
            P.ts("vector", t["kq"][:], t["k"][:], vec[:, 15, oc:oc + 1], None, ALU.mult, None, ["k", "vec"], ["kq"])
            P.act(sqb[:], t["kq"][:], AF.Square, ["kq"], ["sqb"])
            i = nxt("pp")
            P.mm(pp[i], bones[:], sqb[:], True, True, ["bones", "sqb"], [f"pp{i}"])
            P.act(t["lnv"][:], pp[i], AF.Ln, [f"pp{i}"], ["lnv"], bias=1e-12)
            P.act(t["rs"][:], t["lnv"][:], AF.Exp, ["lnv"], ["rs"], scale=-0.5)
            P.tt("vector", t["kkn"][:], t["kq"][:], t["rs"][:], ALU.mult, ["kq", "rs"], ["kkn"])
            yield
            for d in range(2):
                sw, ag, kd, bb = t[f"sw{d}"], t[f"ag{d}"], t[f"kd{d}"], t[f"b{d}"]
                EE = "vector"
                P.ts(EE, t["fac"][:], ag[:], vec[:, 16, oc:oc + 1], vec[:, 17, oc:oc + 1], ALU.mult, ALU.add, [f"ag{d}", "vec"], ["fac"])
                P.tt(EE, kd[:], t["k"][:], t["fac"][:], ALU.mult, ["k", "fac"], [f"kd{d}"])
                P.tt(EE, bb[:], t["kkn"][:], ag[:], ALU.mult, ["kkn", f"ag{d}"], [f"b{d}"])
                P.op("vector", lambda e, sw=sw: e.tensor_tensor_scan(out=t["L"][:], data0=rmask[:], data1=sw[:], initial=0.0,
                                                                      op0=ALU.mult, op1=ALU.add), [f"sw{d}", "rmask"], ["L"])
                L3 = v3(t["L"][:])
                if d == 0:
                    P.tt(EE, t["Lx"][:], t["L"][:], sw[:], ALU.subtract, ["L", f"sw{d}"], ["Lx"])
                    Li, Lin = t["L"], "L"
                else:
                    P.tt(EE, v3(t["Lx"][:]), L3[:, :, 63:64].broadcast_to([128, 4, 64]), L3, ALU.subtract, ["L"], ["Lx"])
                    P.tt(EE, t["Lb"][:], t["Lx"][:], sw[:], ALU.add, ["Lx", f"sw{d}"], ["Lb"])
                    Li, Lin = t["Lb"], "Lb"
                yield
                P.act(t["E1"][:], Li[:], AF.Exp, [Lin], ["E1"], scale=-C0)
                P.act(t["E3"][:], Li[:], AF.Exp, [Lin], ["E3"], scale=C0)
                P.act(t["E2"][:], t["Lx"][:], AF.Exp, ["Lx"], ["E2"], scale=-C0)
                yield
                P.stt(ops_st[:, d, 0, :], t["kkn"][:], -1.0, t["E2"][:], ALU.mult, ALU.mult, ["kkn", "E2"], ["ops_st"])
                P.tt("vector", ops_st[:, d, 1, :], t["r"][:], t["E1"][:], ALU.mult, ["r", "E1"], ["ops_st"])
                P.tt(EE, ops_st[:, d, 2, :], kd[:], t["E3"][:], ALU.mult, [f"kd{d}", "E3"], ["ops_st"])
                P.tt(EE, ops_st[:, d, 3, :], bb[:], t["E3"][:], ALU.mult, [f"b{d}", "E3"], ["ops_st"])
                E13 = v3(t["E1"][:])
                gsrc = E13[:, :, 63] if d == 0 else E13[:, :, 0]
                P.cp("vector", gam_st[:, d, :], gsrc, ["E1"], ["gam_st"])
                if d == 1:
                    P.cp("gpsimd", gamb_t[:, oc, :], gam_st[:, 1, :], ["gam_st"], ["gamb_t"])
                yield
            P.tt("vector", t["ks"][:], t["kd0"][:], t["kd1"][:], ALU.add, ["kd0", "kd1"], ["ks"])
            for h2 in range(2):
                sl = slice(h2 * 64, (h2 + 1) * 64)
                P.stt(RK[sl, :, h2, :], v3(t["r"][sl, :]), vec[sl, 18, oc:oc + 1], v3(t["ks"][sl, :]), ALU.mult, ALU.mult, ["r", "ks", "vec"], ["RK"])
            i = nxt("pp")
            for u in range(4):
                P.mm(pp[i][:, u:u + 1], RK[:, u, :, :].rearrange("p h s -> p (h s)"), onesb[:, 0:1], True, True, ["RK", "onesb"], [f"pp{i}"])
            P.cp("scalar", bon_t[:, oc, :], pp[i][:, 0:4], [f"pp{i}"], ["bon_t"])
            P.dma("sync", S["ops"][tg, oc], ops_st[:].rearrange("p d x n -> p (d x n)"), reads=["ops_st"], writes=[("ops", tg, oc)], sem="ops_st")
            P.dma("sync", S["vb"][tg, oc], vb_st[:].rearrange("p u s -> p (u s)"), reads=["vb_st"], writes=[("vb", tg, oc)], sem="vb_st")
            P.dma("sync", S["gam"][tg, oc], gam_st[:].rearrange("p d u -> p (d u)"), reads=["gam_st"], writes=[("gam", tg, oc)], sem="gam_st")
            yield


        NT = len(RW_ORDER1)
        load_hh(0)
        for ti in range(NT):
            isctx, idx = RW_ORDER1[ti]
            tg = 16 if isctx else idx
            for _ in tprep(ti):
                pass
            jobs = [(oc % NSET, prep(ti, oc, oc % NSET)) for oc in range(8)]
            active = []
            since = 99
            while jobs or active:
                if jobs and len(active) < NSET and (since >= 4 or not active):
                    active.append(jobs.pop(0))
                    since = 0
                since += 1
                for item in list(active):
                    P.ns = item[0]
                    try:
                        next(item[1])
                    except StopIteration:
                        active.remove(item)
                    P.ns = None
            P.dma("sync", S["gamb"][tg], gamb_t[:].rearrange("p a b -> p (a b)"), reads=["gamb_t"], writes=[("gamb", tg)], sem="gamb_t")
            P.dma("sync", S["bon"][tg], bon_t[:].rearrange("p a b -> p (a b)"), reads=["bon_t"], writes=[("bon", tg)], sem="bon_t")
        P.ns_set = frozenset()


def stage_rwkv1b(P, io, G, S):
    vec, masks, identb, identf, bones, onesb, rmask = (G[k] for k in ("vec", "masks", "identb", "identf", "bones", "onesb", "rmask"))
    with P.phase("rwkv1b"):
        YPs = P.sb([128, 4, 64], F32)
        SAs = P.sb([128, 4, 64], F32)
        Sf = P.sb([128, 8, 64], BF16)
        ARq = [[P.sb([128, 4, 2, 128], BF16, f"AR{q}{d}") for d in range(2)] for q in range(3)]
        KTq = [[P.sb([128, 4, 128], BF16, f"KT{q}{d}") for d in range(2)] for q in range(3)]
        BTq = [[P.sb([128, 4, 128], BF16, f"BT{q}{d}") for d in range(2)] for q in range(3)]
        stg = [P.sb([128, 2, 4, 256], BF16, f"stg{q}") for q in range(3)]
        Vbq = [P.sb([128, 4, 64], BF16, f"Vb{q}") for q in range(4)]
        gamq = [P.sb([128, 2, 4], F32, f"gam{q}") for q in range(4)]
        inv2 = []
        for q in range(2):
            row = []
            for d in range(2):
                st = {}
                for nm, shp in (("Atok", [128, 4, 128]), ("Btok", [128, 4, 128]), ("MQ", [128, 4, 256]), ("MWa", [128, 4, 2, 128]),
                                ("MWb", [128, 4, 2, 128]), ("MTa", [128, 4, 128]), ("MTb", [128, 4, 128])):
                    st[nm] = P.sb(shp, BF16, f"i{q}{d}_{nm}")
                row.append(st)
            inv2.append(row)
        fin = []
        for q in range(2):
            row = []
            for d in range(2):
                st = {}
                for nm, shp in (("Ktok", [128, 4, 128]), ("NP", [128, 4, 256]), ("XW", [128, 4, 256]), ("NVb", [128, 4, 64]),
                                ("GY", [128, 4, 128]), ("GS", [128, 4, 128])):
                    st[nm] = P.sb(shp, BF16, f"f{q}{d}_{nm}")
                row.append(st)
            fin.append(row)
        pf = P.ps([128, 512], F32)
        pb = [P.ps([128, 512], F32) for _ in range(7)]
        cnt = {"pb": 0}
        nmod = {"pb": 7}

        def nxt(kind):
            i = cnt[kind] % nmod[kind]
            cnt[kind] += 1
            return i

        for q in range(3):
            for d in range(2):
                P.memset("gpsimd", ARq[q][d][:], 0.0, [f"AR{q}{d}"])
                P.memset("gpsimd", KTq[q][d][:], 0.0, [f"KT{q}{d}"])
                P.memset("gpsimd", BTq[q][d][:], 0.0, [f"BT{q}{d}"])
        P.memset("gpsimd", Sf[:], 0.0, [("Sf", p) for p in range(8)])

        def v3(ap):
            return ap.rearrange("p (u s) -> p u s", s=64)

        def u128(ap):
            return ap.rearrange("p (u x) -> p u x", x=128)

        def loadjob(ti, oc, a, z):
            isctx, idx = RW_ORDER1[ti]
            tg = 16 if isctx else idx
            sg_ = stg[a]
            P.dma("sync", sg_[:].rearrange("p d x n -> p (d x n)"), S["ops"][tg, oc], writes=[f"stg{a}"], sem=f"stg{a}")
            P.dma("sync", Vbq[z][:].rearrange("p u s -> p (u s)"), S["vb"][tg, oc], writes=[f"Vb{z}"], sem=f"Vb{z}")
            P.dma("sync", gamq[z][:].rearrange("p d u -> p (d u)"), S["gam"][tg, oc], writes=[f"gam{z}"], sem=f"gam{z}")
            yield
            for d in range(2):
                ar5 = ARq[a][d][:].rearrange("p u a (h s) -> p u a h s", h=2)
                kt4 = KTq[a][d][:].rearrange("p u (h s) -> p u h s", h=2)
                bt4 = BTq[a][d][:].rearrange("p u (h s) -> p u h s", h=2)
                for h2 in range(2):
                    sl = slice(h2 * 64, (h2 + 1) * 64)
                    P.cp("gpsimd", ar5[sl, :, 0, h2, :], v3(sg_[sl, d, 0, :]), [f"stg{a}"], [f"AR{a}{d}"])
                    P.cp("gpsimd", ar5[sl, :, 1, h2, :], v3(sg_[sl, d, 1, :]), [f"stg{a}"], [f"AR{a}{d}"])
                    P.cp("gpsimd", kt4[sl, :, h2, :], v3(sg_[sl, d, 2, :]), [f"stg{a}"], [f"KT{a}{d}"])
                    P.cp("gpsimd", bt4[sl, :, h2, :], v3(sg_[sl, d, 3, :]), [f"stg{a}"], [f"BT{a}{d}"])
                    yield

        def chain(ti, oc, q, d, z, a):
            AR, KT, BT, Vb = ARq[a][d], KTq[a][d], BTq[a][d], Vbq[z]
            ARn, KTn, BTn, Vbn = f"AR{a}{d}", f"KT{a}{d}", f"BT{a}{d}", f"Vb{z}"
            iv, fn = inv2[q][d], fin[q][d]
            IR = lambda nm: f"i{q}{d}_{nm}"
            FR = lambda nm: f"f{q}{d}_{nm}"
            mS, mC = (0, 2) if d == 0 else (2, 0)
            mSI = masks[:, mS:mS + 2, :].rearrange("p a b -> p (a b)").unsqueeze(1).broadcast_to([128, 4, 256])
            mCb = masks[:, mC, :].unsqueeze(1).broadcast_to([128, 4, 128])
            idb = identb[:].unsqueeze(1).broadcast_to([128, 4, 128])
            for src, srcn, dst, dstn in ((AR[:, :, 0, :], ARn, iv["Atok"], IR("Atok")), (BT[:], BTn, iv["Btok"], IR("Btok")),
                                         (KT[:], KTn, fn["Ktok"], FR("Ktok"))):
                j = nxt("pb")
                pbt = pb[j][:].bitcast(BF16)
                for u in range(4):
                    P.tr(pbt[:, u * 128:(u + 1) * 128], src[:, u, :], identb[:], [srcn, "identb"], [f"pb{j}"])
                P.cp("scalar", dst[:].rearrange("p u x -> p (u x)"), pbt[:, 0:512], [f"pb{j}"], [dstn])
            mSb = masks[:, mS, :].unsqueeze(1).broadcast_to([128, 4, 128])
            mIb = masks[:, mS + 1, :].unsqueeze(1).broadcast_to([128, 4, 128])

            def two_bank(mm_fn):
                j0, j1 = nxt("pb"), nxt("pb")
                for u in range(4):
                    mm_fn(u, pb[j0][:, u * 128:(u + 1) * 128], f"pb{j0}", pb[j1][:, u * 128:(u + 1) * 128], f"pb{j1}")
                return j0, j1

            for lhs, lhsn, dst, dstn in ((BT, BTn, iv["MQ"], IR("MQ")), (KT, KTn, fn["NP"], FR("NP"))):
                def mm_ab(u, o0, n0, o1, n1, lhs=lhs, lhsn=lhsn):
                    P.mm(o0, lhs[:, u, :], AR[:, u, 0, :], True, True, [lhsn, ARn], [n0])
                    P.mm(o1, lhs[:, u, :], AR[:, u, 1, :], True, True, [lhsn, ARn], [n1])
                j0, j1 = two_bank(mm_ab)
                P.tt("vector", dst[:, :, 0:128], u128(pb[j0][:]), mSb, ALU.mult, [f"pb{j0}", "masks"], [dstn])
                P.tt("vector", dst[:, :, 128:256], u128(pb[j1][:]), mIb, ALU.mult, [f"pb{j1}", "masks"], [dstn])
            j = nxt("pb")
            for u in range(4):
                P.mm(pb[j][:, u * 128:(u + 1) * 128], AR[:, u, 0, :], BT[:, u, :], True, True, [ARn, BTn], [f"pb{j}"])
            cur, curn, nx, nxn = iv["MWa"], IR("MWa"), iv["MWb"], IR("MWb")
            P.tt("vector", cur[:, :, 0, :], u128(pb[j][:]), mCb, ALU.mult, [f"pb{j}", "masks"], [curn])
            yield
            j = nxt("pb")
            for u in range(4):
                P.mm(pb[j][:, u * 128:(u + 1) * 128], iv["MQ"][:, u, 0:128], cur[:, u, 0, :], True, True, [IR("MQ"), curn], [f"pb{j}"])
            P.cp("scalar", nx[:, :, 0, :], u128(pb[j][:]), [f"pb{j}"], [nxn])
            P.tt("gpsimd", nx[:, :, 1, :], cur[:, :, 0, :], idb, ALU.add, [curn, "identb"], [nxn])
            j = nxt("pb")
            for u in range(4):
                P.mm(pb[j][:, u * 128:(u + 1) * 128], cur[:, u, 0, :], iv["MQ"][:, u, 0:128], True, True, [IR("MQ"), curn], [f"pb{j}"])
            curT, curTn, nxT, nxTn = iv["MTa"], IR("MTa"), iv["MTb"], IR("MTb")
            P.cp("scalar", curT[:], u128(pb[j][:]), [f"pb{j}"], [curTn])
            cur, curn, nx, nxn = nx, nxn, cur, curn
            yield
            for lev in range(1, 5):
                def mm_lev(u, o0, n0, o1, n1, cur=cur, curn=curn, curT=curT, curTn=curTn):
                    P.mm(o0, curT[:, u, :], cur[:, u, 0, :], True, True, [curTn, curn], [n0])
                    P.mm(o1, curT[:, u, :], cur[:, u, 1, :], True, True, [curTn, curn], [n1])
                j0, j1 = two_bank(mm_lev)
                P.cp("scalar", nx[:, :, 0, :], u128(pb[j0][:]), [f"pb{j0}"], [nxn])
                P.tt("vector", nx[:, :, 1, :], u128(pb[j1][:]), cur[:, :, 1, :], ALU.add, [f"pb{j1}", curn], [nxn])
                j = nxt("pb")
                for u in range(4):
                    P.mm(pb[j][:, u * 128:(u + 1) * 128], cur[:, u, 0, :], curT[:, u, :], True, True, [curn, curTn], [f"pb{j}"])
                P.cp("scalar", nxT[:], u128(pb[j][:]), [f"pb{j}"], [nxTn])
                cur, curn, nx, nxn = nx, nxn, cur, curn
                curT, curTn, nxT, nxTn = nxT, nxTn, curT, curTn
                yield
            j = nxt("pb")
            for u in range(4):
                P.mm(pb[j][:, u * 128:(u + 1) * 128], curT[:, u, :], cur[:, u, 1, :], True, True, [curTn, curn], [f"pb{j}"])
            P.tt("vector", nx[:, :, 1, :], u128(pb[j][:]), cur[:, :, 1, :], ALU.add, [f"pb{j}", curn], [nxn])
            W6, W6n = nx, nxn
            j = nxt("pb")
            for u in range(4):
                P.mm(pb[j][:, u * 64:(u + 1) * 64], fn["NP"][:, u, 0:128], Vb[:, u, :], True, True, [FR("NP"), Vbn], [f"pb{j}"])
            P.cp("scalar", fn["NVb"][:].rearrange("p u x -> p (u x)"), pb[j][:, 0:256], [f"pb{j}"], [FR("NVb")])
            yield

            def mm_d(u, o0, n0, o1, n1):
                P.mm(o0, W6[:, u, 1, :], iv["MQ"][:, u, 128:256], True, True, [W6n, IR("MQ")], [n0])
                P.mm(o1, W6[:, u, 1, :], iv["Btok"][:, u, :], True, True, [W6n, IR("Btok")], [n1])
            j0, j1 = two_bank(mm_d)
            P.cp("scalar", fn["XW"][:, :, 0:128], u128(pb[j0][:]), [f"pb{j0}"], [FR("XW")])
            P.cp("vector", fn["XW"][:, :, 128:256], u128(pb[j1][:]), [f"pb{j1}"], [FR("XW")])
            yield

            def mm_f(u, o0, n0, o1, n1):
                P.mm(o0, iv["Atok"][:, u, :], fn["XW"][:, u, 0:128], True, True, [IR("Atok"), FR("XW")], [n0])
                P.mm(o1, iv["Atok"][:, u, :], fn["XW"][:, u, 128:256], True, True, [IR("Atok"), FR("XW")], [n1])
            j0, j1 = two_bank(mm_f)
            P.tt("vector", fn["GY"][:], u128(pb[j0][:]), AR[:, :, 1, :], ALU.add, [f"pb{j0}", ARn], [FR("GY")])
            P.tt("vector", fn["GS"][:], u128(pb[j1][:]), idb, ALU.add, [f"pb{j1}", "identb"], [FR("GS")])
            yield

        def finish(ti, oc, q, z):
            isctx, idx = RW_ORDER1[ti]
            tg = 16 if isctx else idx
            sf, sb_ = fin[q]
            F0 = lambda nm: f"f{q}0_{nm}"
            F1 = lambda nm: f"f{q}1_{nm}"
            Vb, Vbn, gamz = Vbq[z], f"Vb{z}", gamq[z]
            SFR = ("Sf", oc)
            for u in range(4):
                yo = pf[:, u * 64:(u + 1) * 64]
                P.mm(yo, sf["NP"][:, u, 128:256], Vb[:, u, :], True, False, [F0("NP"), Vbn], ["pf"])
                P.mm(yo, sf["XW"][:, u, 0:128], sf["NVb"][:, u, :], False, False, [F0("XW"), F0("NVb")], ["pf"])
                P.mm(yo, sb_["NP"][:, u, 128:256], Vb[:, u, :], False, False, [F1("NP"), Vbn], ["pf"])
                P.mm(yo, sb_["XW"][:, u, 0:128], sb_["NVb"][:, u, :], False, False, [F1("XW"), F1("NVb")], ["pf"])
                P.mm(yo, sf["GY"][:, u, :], Sf[:, oc, :], False, True, [F0("GY"), SFR], ["pf"])
                so = pf[:, 256:320]
                P.mm(so, sf["Ktok"][:, u, :], Vb[:, u, :], True, False, [F0("Ktok"), Vbn], ["pf"])
                P.mm(so, sf["XW"][:, u, 128:256], sf["NVb"][:, u, :], False, False, [F0("XW"), F0("NVb")], ["pf"])
                P.mm(so, sf["GS"][:, u, :], Sf[:, oc, :], False, True, [F0("GS"), SFR], ["pf"])
                P.ts("vector", Sf[:, oc, :], so, gamz[:, 0, u:u + 1], None, ALU.mult, None, ["pf", f"gam{z}"], [SFR])
                yield
            P.cp("vector", YPs[:].rearrange("p u x -> p (u x)"), pf[:, 0:256], ["pf"], ["YPs"])
            P.dma("sync", S["yp"][tg, oc], YPs[:].rearrange("p u x -> p (u x)"), reads=["YPs"], writes=[("yp", tg, oc)], sem="YPs")
            j = nxt("pb")
            for u in range(4):
                so = pb[j][:, u * 64:(u + 1) * 64]
                P.mm(so, sb_["Ktok"][:, u, :], Vb[:, u, :], True, False, [F1("Ktok"), Vbn], [f"pb{j}"])
                P.mm(so, sb_["XW"][:, u, 128:256], sb_["NVb"][:, u, :], False, True, [F1("XW"), F1("NVb")], [f"pb{j}"])
            P.cp("scalar", SAs[:].rearrange("p u x -> p (u x)"), pb[j][:, 0:256], [f"pb{j}"], ["SAs"])
            P.dma("sync", S["sadd"][tg, oc], SAs[:].rearrange("p u x -> p (u x)"), reads=["SAs"], writes=[("sadd", tg, oc)], sem="SAs")
            P.dma("sync", S["gyb"][tg, oc], sb_["GY"][:].rearrange("p u x -> p (u x)"), reads=[F1("GY")], writes=[("gyb", tg, oc)], sem=F1("GY"))
            P.dma("sync", S["gsb"][tg, oc], sb_["GS"][:].rearrange("p u x -> p (u x)"), reads=[F1("GS")], writes=[("gsb", tg, oc)], sem=F1("GS"))
            yield


        NT = len(RW_ORDER1)
        NJ = NT * 8
        donef = set()

        def stream_L():
            for k in range(NJ):
                ti, oc = divmod(k, 8)
                yield ("load", k, lambda k=k: ((k < 3 or (("c0", k - 3) in donef and ("c1", k - 3) in donef)) and (k < 4 or ("fin", k - 4) in donef)),
                       lambda ti=ti, oc=oc, k=k: loadjob(ti, oc, k % 3, k % 4))

        def stream_C(d, par):
            for k in range(par, NJ, 2):
                ti, oc = divmod(k, 8)
                yield (f"c{d}", k, lambda k=k: (("load", k) in donef and (k < 2 or ("fin", k - 2) in donef)),
                       lambda ti=ti, oc=oc, k=k: chain(ti, oc, k % 2, d, k % 4, k % 3))

        def stream_F():
            for k in range(NJ):
                ti, oc = divmod(k, 8)
                yield ("fin", k, lambda k=k: (("c0", k) in donef and ("c1", k) in donef),
                       lambda ti=ti, oc=oc, k=k: finish(ti, oc, k % 2, k % 4))

        streams = [stream_L(), stream_C(0, 0), stream_C(1, 0), stream_C(0, 1), stream_C(1, 1), stream_F()]
        NS_ = len(streams)
        cur = [None] * NS_
        pend = [None] * NS_
        alive = [True] * NS_
        while any(alive):
            progressed = False
            for si in range(NS_):
                if not alive[si]:
                    continue
                if cur[si] is None:
                    if pend[si] is None:
                        try:
                            pend[si] = next(streams[si])
                        except StopIteration:
                            alive[si] = False
                            continue
                    kind, k, ready, mk = pend[si]
                    if not ready():
                        continue
                    cur[si] = (kind, k, mk())
                    pend[si] = None
                kind, k, gen = cur[si]
                try:
                    next(gen)
                    progressed = True
                except StopIteration:
                    donef.add((kind, k))
                    cur[si] = None
                    progressed = True
            assert progressed or not any(alive), "scheduler stuck"


def stage_rwkv2(P, io, G, S, src, xa):
    vec, identb = G["vec"], G["identb"]
    GN_EPS = 64e-5
    with P.phase("rwkv2"):
        wo = P.sb([64, 16, 1024], BF16)
        P.dma("gpsimd", wo[:], io["rwkv_wo"].rearrange("(h v) f -> v h f", v=64), writes=["wo"], sem="wo")
        lnw = P.sb([128, 8, 64], F32)
        lnb = P.sb([128, 8, 64], F32)
        P.dma("sync", lnw[:], io["lnw_st"], writes=["lnw"], sem="lnw")
        P.dma("sync", lnb[:], io["lnb_st"], writes=["lnb"], sem="lnb")
        big = {}
        for nm in ("yp", "sadd", "vst", "gst"):
            big[nm] = [P.sb([128, 8, 256], F32, f"l_{nm}{b}") for b in range(2)]
        for nm in ("gyb", "gsb"):
            big[nm] = [P.sb([128, 8, 512], BF16, f"l_{nm}{b}") for b in range(2)]
        gamb = [P.sb([128, 8, 4], F32) for _ in range(2)]
        bon = [P.sb([128, 8, 4], F32) for _ in range(2)]
        xt = [P.sb([128, 8, 256], F32) for _ in range(2)]
        Sb = P.sb([128, 8, 64], BF16)
        ysb2 = [P.sb([128, 8, 64], F32) for _ in range(2)]
        ysq2 = [P.sb([128, 8, 64], F32) for _ in range(2)]
        tmpS = P.sb([128, 8, 64], F32)
        yn2 = [P.sb([128, 8, 64], F32) for _ in range(2)]
        bv2 = [P.sb([128, 8, 64], F32) for _ in range(2)]
        ob2 = [P.sb([128, 8, 64], BF16) for _ in range(2)]
        st2 = [{nm: P.sb([128, 8], F32, f"g{k_}_" + nm) for nm in ("s1", "s2", "mean", "msq", "var", "lnv", "rstd")} for k_ in range(2)]
        OT = P.sb([64, 16, 256], BF16)
        py = [P.ps([128, 512], F32) for _ in range(2)]
        pS = P.ps([128, 512], F32)
        ptr = P.ps([128, 1024], F32)
        pw = [P.ps([128, 512], F32) for _ in range(2)]
        P.memset("gpsimd", Sb[:], 0.0, ["Sb"])

        def load(k):
            isctx, idx = RW_ORDER2[k]
            tg = 16 if isctx else idx
            b = k % 2
            for nm in ("yp", "sadd", "vst", "gst", "gyb", "gsb"):
                P.dma("sync", big[nm][b][:], S[nm][tg].rearrange("o p x -> p o x"), writes=[f"{nm}{b}"], sem=f"{nm}{b}")
            P.dma("sync", gamb[b][:].rearrange("p a b -> p (a b)"), S["gamb"][tg], writes=[f"gamb{b}"], sem=f"gamb{b}")
            P.dma("sync", bon[b][:].rearrange("p a b -> p (a b)"), S["bon"][tg], writes=[f"bon{b}"], sem=f"bon{b}")
            c0 = T if isctx else idx * 256
            P.dma("sync", xt[b][:], fm(src[:, c0:c0 + 256]), writes=[f"xt{b}"], sem=f"xt{b}")

        load(0)
        for k, (isctx, idx) in enumerate(RW_ORDER2):
            b = k % 2
            if k + 1 < len(RW_ORDER2):
                load(k + 1)
            c0 = T if isctx else idx * 256
            _, _, gates = mod_scalars(G, 0, 0, isctx)
            bc = lambda ap: ap.unsqueeze(2).broadcast_to([128, 8, 64])
            def chain_part(u):
                us = slice(u * 64, (u + 1) * 64)
                q_ = u % 2
                for oc in range(8):
                    P.mm(py[q_][:, oc * 64:(oc + 1) * 64], big["gyb"][b][:, oc, u * 128:(u + 1) * 128], Sb[:, oc, :], True, True, [f"gyb{b}", "Sb"], [f"py{q_}"])
                for oc in range(8):
                    P.mm(pS[:, oc * 64:(oc + 1) * 64], big["gsb"][b][:, oc, u * 128:(u + 1) * 128], Sb[:, oc, :], True, True, [f"gsb{b}", "Sb"], ["pS"])
                pS3 = pS[:].rearrange("p (o v) -> p o v", v=64)
                P.tt("vector", tmpS[:], pS3, big["sadd"][b][:, :, us], ALU.add, ["pS", f"sadd{b}"], ["tmpS"])
                P.tt("vector", Sb[:], tmpS[:], bc(gamb[b][:, :, u]), ALU.mult, ["tmpS", f"gamb{b}"], ["Sb"])

            def read_part(u):
                us = slice(u * 64, (u + 1) * 64)
                q_ = u % 2
                ysb, ysq, yn, bv, ob, st = ysb2[q_], ysq2[q_], yn2[q_], bv2[q_], ob2[q_], st2[q_]
                N = lambda nm: f"{nm}{q_}"
                py3 = py[q_][:].rearrange("p (o v) -> p o v", v=64)
                P.tt("vector", ysb[:], py3, big["yp"][b][:, :, us], ALU.add, [f"py{q_}", f"yp{b}"], [N("ysb")])
                P.tt("gpsimd", bv[:], big["vst"][b][:, :, us], bc(bon[b][:, :, u]), ALU.mult, [f"vst{b}", f"bon{b}"], [N("bv")])
                yield
                P.op("vector", lambda e: e.tensor_reduce(out=st["s1"][:], in_=ysb[:], axis=AX.X, op=ALU.add), [N("ysb")], [N("s1")])
                P.tt("gpsimd", ysq[:], ysb[:], ysb[:], ALU.mult, [N("ysb")], [N("ysq")])
                yield
                P.op("vector", lambda e: e.tensor_reduce(out=st["s2"][:], in_=ysq[:], axis=AX.X, op=ALU.add), [N("ysq")], [N("s2")])
                P.ts("vector", st["mean"][:], st["s1"][:], 1.0 / 64, None, ALU.mult, None, [N("s1")], [N("mean")])
                P.tt("vector", st["msq"][:], st["mean"][:], st["mean"][:], ALU.mult, [N("mean")], [N("msq")])
                P.stt(st["var"][:], st["s2"][:], 1.0 / 64, st["msq"][:], ALU.mult, ALU.subtract, [N("s2"), N("msq")], [N("var")])
                yield
                P.act(st["lnv"][:], st["var"][:], AF.Ln, [N("var")], [N("lnv")], bias=GN_EPS)
                P.act(st["rstd"][:], st["lnv"][:], AF.Exp, [N("lnv")], [N("rstd")], scale=-0.5)
                P.tt("gpsimd", yn[:], ysb[:], bc(st["mean"][:]), ALU.subtract, [N("ysb"), N("mean")], [N("yn")])
                yield
                P.tt("vector", yn[:], yn[:], bc(st["rstd"][:]), ALU.mult, [N("yn"), N("rstd")], [N("yn")])
                yield
                P.tt("gpsimd", yn[:], yn[:], lnw[:], ALU.mult, [N("yn"), "lnw"], [N("yn")])
                yield
                P.tt("vector", yn[:], yn[:], lnb[:], ALU.add, [N("yn"), "lnb"], [N("yn")])
                yield
                P.tt("gpsimd", yn[:], yn[:], bv[:], ALU.add, [N("yn"), N("bv")], [N("yn")])
                yield
                P.tt("vector", ob[:], yn[:], big["gst"][b][:, :, us], ALU.mult, [N("yn"), f"gst{b}"], [N("ob")])
                yield
                ptb = ptr[:].bitcast(BF16)
                for oc in range(8):
                    P.tr(ptb[0:64, oc * 128:(oc + 1) * 128], ob[:, oc, :], identb[:], [N("ob"), "identb"], ["ptr"])
                P.cp("scalar", OT[:, :, us], ptb[0:64, 0:1024].rearrange("p (h t) -> p h t", t=64), ["ptr"], ["OT"])
                yield

            def chain_all():
                for u in range(3, -1, -1):
                    chain_part(u)
                    yield

            jobs = [read_part(u) for u in range(3, -1, -1)]
            cgen = chain_all()
            next(cgen)
            active = []
            started = 0
            while jobs or active:
                while jobs and len(active) < 2:
                    if started >= 1:
                        try:
                            next(cgen)
                        except StopIteration:
                            pass
                    active.append(jobs.pop(0))
                    started += 1
                for gen in list(active):
                    try:
                        next(gen)
                    except StopIteration:
                        active.remove(gen)
            for oc in range(8):
                j = oc % 2
                for h in range(16):
                    P.mm(pw[j][:, 0:256], wo[:, h, oc * 128:(oc + 1) * 128], OT[:, h, :], h == 0, h == 15, ["wo", "OT"], [f"pw{j}"])
                P.stt(xt[b][:, oc, :], pw[j][:, 0:256], gates[oc], xt[b][:, oc, :], ALU.mult, ALU.add, [f"pw{j}", f"xt{b}", "modv"], [f"xt{b}"])
            P.dma("sync", fm(xa[:, c0:c0 + 256]), xt[b][:], reads=[f"xt{b}"], writes=[("xa", k)], sem=f"xt{b}")


def stage_qkv(P, io, G, hb, qtd, Kz, VA):
    vec, bones, perm = G["vec"], G["bones"], G["perm"]
    with P.phase("qkv"):
        wq = P.sb([128, 8, 1024], BF16)
        wkd = P.sb([128, 8, 512], BF16)
        wv = P.sb([128, 8, 256], BF16)
        P.dma("gpsimd", wq[:], fm(io["attn_wq"]), writes=["wq"], sem="wq")
        P.dma("gpsimd", wkd[:], fm(io["attn_wkd"]), writes=["wkd"], sem="wkd")
        P.dma("gpsimd", wv[:], fm(io["attn_wv"]), writes=["wv"], sem="wv")
        ht = [P.sb([128, 8, 512], BF16) for _ in range(2)]
        cs = [P.sb([128, 512], F32) for _ in range(2)]
        sn = [P.sb([128, 512], F32) for _ in range(2)]
        NB = 2
        qf = [P.sb([128, 512], F32) for _ in range(NB)]
        sqb = [P.sb([128, 512], BF16) for _ in range(NB)]
        lnv = [P.sb([128, 512], F32) for _ in range(NB)]
        rstd = [P.sb([128, 512], F32) for _ in range(NB)]
        qh = [P.sb([128, 512], F32) for _ in range(NB)]
        qhb = [P.sb([128, 512], BF16) for _ in range(NB)]
        t1 = [P.sb([128, 512], F32) for _ in range(NB)]
        t2 = [P.sb([128, 512], F32) for _ in range(NB)]
        qst = [P.sb([128, 8, 512], BF16) for _ in range(2)]
        pp = [P.ps([128, 512], F32) for _ in range(6)]
        cnt = [0, 0]

        def nxt():
            cnt[0] += 1
            return cnt[0] % 6

        P.memset("gpsimd", VA[:], 0.0, ["VA0"])
        P.memset("gpsimd", VA[:].rearrange("p k (j x) -> p k j x", x=65)[:, :, 0:5, 64:65], 1.0, ["VA0"])
        P.memset("gpsimd", Kz[0][64:128, :, :], 0.0, ["Kz0z"])
        P.memset("gpsimd", Kz[1][0:64, :, :], 0.0, ["Kz1z"])
        tiles = ALL_TILES

        def load(i):
            c0, tw, isctx = tiles[i]
            b = i % 2
            P.dma("sync", ht[b][:, :, :tw], fm(hb[:, c0:c0 + tw]), writes=[f"ht{b}"], sem=f"ht{b}")
            if not isctx:
                P.dma("sync", cs[b][:, :tw], io["cosT"][:, c0:c0 + tw], writes=[f"cs{b}"], sem=f"cs{b}")
                P.dma("sync", sn[b][:, :tw], io["sinT"][:, c0:c0 + tw], writes=[f"sn{b}"], sem=f"sn{b}")

        def normrope(wcols, nscal, dsts, b, tw, isctx, wname, dres="dstqk"):
            cnt[1] += 1
            n = cnt[1] % NB
            i = nxt()
            for c in range(8):
                P.mm(pp[i][:, :tw], wcols(c), ht[b][:, c, :tw], c == 0, c == 7, [wname, f"ht{b}"], [f"pp{i}"])
            P.cp("scalar", qf[n][:, :tw], pp[i][:, :tw], [f"pp{i}"], [f"qf{n}"])
            P.act(sqb[n][:, :tw], qf[n][:, :tw], AF.Square, [f"qf{n}"], [f"sqb{n}"])
            yield
            i = nxt()
            P.mm(pp[i][:, :tw], bones[:], sqb[n][:, :tw], True, True, ["bones", f"sqb{n}"], [f"pp{i}"])
            P.act(lnv[n][:, :tw], pp[i][:, :tw], AF.Ln, [f"pp{i}"], [f"lnv{n}"], bias=1e-6, scale=1.0 / 64)
            P.act(rstd[n][:, :tw], lnv[n][:, :tw], AF.Exp, [f"lnv{n}"], [f"rstd{n}"], scale=-0.5)
            yield
            P.stt(qh[n][:, :tw], qf[n][:, :tw], nscal, rstd[n][:, :tw], ALU.mult, ALU.mult, [f"qf{n}", f"rstd{n}", "vec"], [f"qh{n}"])
            if isctx:
                for dst, sl in dsts:
                    P.cp("vector", dst, qh[n][sl, :tw], [f"qh{n}"], [dres])
                return
            P.cp("vector", qhb[n][:, :tw], qh[n][:, :tw], [f"qh{n}"], [f"qhb{n}"])
            yield
            i = nxt()
            P.mm(pp[i][:, :tw], perm[:], qhb[n][:, :tw], True, True, ["perm", f"qhb{n}"], [f"pp{i}"])
            P.tt("vector", t1[n][:, :tw], qh[n][:, :tw], cs[b][:, :tw], ALU.mult, [f"qh{n}", f"cs{b}"], [f"t1{n}"])
            P.tt("vector", t2[n][:, :tw], pp[i][:, :tw], sn[b][:, :tw], ALU.mult, [f"pp{i}", f"sn{b}"], [f"t2{n}"])
            yield
            for dst, sl in dsts:
                P.tt("vector", dst, t1[n][sl, :tw], t2[n][sl, :tw], ALU.add, [f"t1{n}", f"t2{n}"], [dres])

        ALLP = slice(0, 128)
        load(0)
        for i, (c0, tw, isctx) in enumerate(tiles):
            b = i % 2
            if i + 1 < len(tiles):
                load(i + 1)
            jobs = []
            if not isctx:
                for oc in range(8):
                    jobs.append(normrope(lambda c, oc=oc: wq[:, c, oc * 128:(oc + 1) * 128], vec[:, 19, oc:oc + 1], [(qst[b][:, oc, :tw], ALLP)], b, tw, False, "wq",
                                         dres=(f"qst{b}", oc)))
            for g in range(4):
                jobs.append(normrope(lambda c, g=g: wkd[:, c, g * 128:(g + 1) * 128], vec[:, 20, 0:1],
                                     [(Kz[0][0:64, g, c0:c0 + tw], slice(0, 64)), (Kz[1][64:128, g, c0:c0 + tw], slice(64, 128))], b, tw, isctx, "wkd"))

            def vjob():
                for sub in range(tw // 128):
                    kt = c0 // 128 + sub
                    j = nxt()
                    for c in range(8):
                        P.mm(pp[j][:, 0:256], ht[b][:, c, sub * 128:(sub + 1) * 128], wv[:, c, :], c == 0, c == 7, ["wv", f"ht{b}"], [f"pp{j}"])
                    P.cp("scalar", VA[:, kt, 65:325].rearrange("p (g x) -> p g x", x=65)[:, :, 0:64],
                         pp[j][:, 0:256].rearrange("p (g d) -> p g d", d=64), [f"pp{j}", "VA0"], [("VA", kt)])
                    yield

            jobs.append(vjob())
            active = []
            while jobs or active:
                while jobs and len(active) < 2:
                    active.append(jobs.pop(0))
                for gen in list(active):
                    try:
                        next(gen)
                    except StopIteration:
                        active.remove(gen)
            if not isctx:
                P.dma("sync", fm(qtd[:, c0:c0 + tw]), qst[b][:, :, :tw], reads=[(f"qst{b}", oc) for oc in range(8)], writes=[("qtd", i)], sem=f"qst{b}")


def stage_attn(P, io, G, qtd, Kz, VA, xa):
    with P.phase("attn"):
        wo = P.sb([128, 8, 1024], BF16)
        P.dma("gpsimd", wo[:], fm(io["attn_wo"]), writes=["wo"], sem="wo")
        sel = P.sb([128, 2, 128], F32)
        P.dma("sync", sel[:], io["c_sel"], writes=["sel"], sem="sel")
        PT = [P.sb([128, 1024], BF16) for _ in range(3)]
        osb = [P.sb([128, 512], F32) for _ in range(2)]
        rb = [P.sb([128, 512], F32) for _ in range(2)]
        xt = P.sb([128, 8, 512], F32)
        QB = [P.sb([128, 8, 512], BF16) for _ in range(2)]
        psS = [P.ps([128, 1024], F32) for _ in range(2)]
        psO = [P.ps([128, 512], F32) for _ in range(2)]
        psB = P.ps([128, 512], F32)
        pX = [P.ps([128, 512], F32) for _ in range(1)]
        _, _, gates = mod_scalars(G, 1, 0, False)
        for k in range(2):
            P.memset("gpsimd", osb[k][:], 0.0, [f"osb{k}"])
        def loadq(qb):
            P.dma("sync", QB[qb % 2][:], fm(qtd[:, qb * 512:(qb + 1) * 512]), writes=[("QT", h, qb) for h in range(16)], sem=f"QB{qb % 2}")

        loadq(0)
        for qb in range(8):
            qsl = slice(qb * 512, (qb + 1) * 512)
            QT = QB[qb % 2]
            if qb + 1 < 8:
                loadq(qb + 1)
            P.dma("sync", xt[:], fm(xa[:, qsl]), writes=["xt"], sem="xt")
            steps = [(h, kp) for h in range(16) for kp in range(17)]

            def S(i):
                h, kp = steps[i]
                g, oc, h2 = h // 4, h // 2, h % 2
                for e_ in range(2):
                    kt = 2 * kp + e_
                    P.mm(psS[i % 2][:, e_ * 512:(e_ + 1) * 512], Kz[h2][:, g, kt * 128:(kt + 1) * 128], QT[:, oc, :], True, True,
                         ["Kz", ("QT", h, qb)], [f"psS{i % 2}"])

            def epi_a(h):
                o = h % 2
                P.cp("vector", osb[o][:], psO[o][:], [f"psO{o}"], [f"osb{o}"])

            def epi_b(h):
                oc, h2, o = h // 2, h % 2, h % 2
                hs = slice(h2 * 64, h2 * 64 + 64)
                P.mm(psB[:, :], sel[:, h2, :], osb[o][:], True, True, ["sel", f"osb{o}"], ["psB"])
                P.op("vector", lambda e, o=o, hs=hs: e.reciprocal(out=rb[o][hs, :], in_=psB[hs, :]), ["psB"], [f"rb{o}"])
                P.tt("gpsimd", QT[hs, oc, :], osb[o][hs, :], rb[o][hs, :], ALU.mult, [f"osb{o}", f"rb{o}"], [("QT", h, qb)])

            S(0)
            pend = {}
            for i, (h, kp) in enumerate(steps):
                g, h2, o = h // 4, h % 2, h % 2
                if i + 1 < len(steps):
                    S(i + 1)
                p_ = i % 3
                P.act(PT[p_][:], psS[i % 2][:, :], AF.Exp, [f"psS{i % 2}"], [f"PT{p_}"], scale=0.125)
                v0 = 65 + 65 * g if h2 == 0 else 1 + 65 * g
                for e_ in range(2):
                    kt = 2 * kp + e_
                    P.mm(psO[o][:, :], VA[:, kt, v0:v0 + 128], PT[p_][:, e_ * 512:(e_ + 1) * 512], kt == 0, kt == 33, [f"PT{p_}", "VA"], [f"psO{o}"])
                if kp == 16:
                    epi_a(h)
                    pend[i + 3] = h
                if i in pend:
                    epi_b(pend.pop(i))
            for k in sorted(pend):
                epi_b(pend[k])
            for oc in range(8):
                j = 0
                for c in range(8):
                    P.mm(pX[j][:, :], wo[:, c, oc * 128:(oc + 1) * 128], QT[:, c, :], c == 0, c == 7,
                         ["wo", ("QT", 2 * c, qb), ("QT", 2 * c + 1, qb)], [f"pX{j}"])
                P.stt(xt[:, oc, :], pX[j][:, :], gates[oc], xt[:, oc, :], ALU.mult, ALU.add, [f"pX{j}", "xt", "modv"], ["xt"])
            P.dma("sync", fm(xa[:, qsl]), xt[:], reads=["xt"], writes=[("xa", qb)], sem="xt")


IN_SHAPES = {
    "xin": [D, TT], "cvec": [128, 8, 2], "w_mod": [2, D, 6 * D], "b_mod": [2, 6 * D], "vecs": [128, NV, 8],
    "mlp_w1": [2, D, 4 * D], "mlp_w2": [2, 4 * D, D],
    "rwkv_wr": [D, D], "rwkv_wk": [D, D], "rwkv_wv": [D, D], "rwkv_wo": [D, D],
    "rwkv_w1": [2, D, 64], "rwkv_w2": [2, 64, D], "rwkv_a1": [2, D, 64], "rwkv_a2": [2, 64, D],
    "rwkv_g1": [D, 128], "rwkv_g2": [128, D], "lnw_st": [128, 8, 64], "lnb_st": [128, 8, 64],
    "attn_wq": [D, D], "attn_wkd": [D, 512], "attn_wv": [D, 256], "attn_wo": [D, D],
    "cosT": [128, T], "sinT": [128, T],
    "c_ident": [128, 128], "c_ones": [128, 128], "c_bones": [128, 128], "c_masks": [128, 4, 128],
    "c_perm": [128, 128], "c_rmask": [128, 256], "c_sel": [128, 2, 128],
}


class IO(dict):
    def __init__(self, nc):
        super().__init__()
        self.nc = nc
        self.used = []

    def __missing__(self, k):
        ap = self.nc.dram_tensor(k, IN_SHAPES[k], F32, kind="ExternalInput").ap()
        self[k] = ap
        self.used.append(k)
        return ap

    def scratch(self, name, shape, dtype):
        return self.nc.dram_tensor(name, list(shape), dtype, kind="Internal").ap()

    def output(self, name, shape, dtype=F32):
        return self.nc.dram_tensor(name, list(shape), dtype, kind="ExternalOutput").ap()


def build(stages="all", dbg=None):
    nc = bass.Bass("TRN2", target_bir_lowering=False)
    io = IO(nc)
    P = Prog(nc)
    G = {}
    outs = {}
    stage_init(P, io, G)
    xa = io.scratch("xa", [D, TT], F32)
    hb = io.scratch("hb", [D, TT], BF16)
    if stages == "t_mlp":
        outs["dbg_h"] = io.output("dbg_h", [D, TT], BF16)
        stage_norm(P, io, G, "n_t", io["xin"], ALL_TILES,
                   lambda ic: mod_scalars(G, 0, 1, ic)[0], lambda ic: mod_scalars(G, 0, 1, ic)[1],
                   lambda c0, tw, ic: fm(hb[:, c0:c0 + tw]), BF16)
        with P.phase("copy"):
            P.dma("sync", xa, io["xin"], writes=["xa"], sem="cpa")
            P.dma("sync", outs["dbg_h"], hb, writes=["o"], sem="cpb")
        stage_mlp(P, io, G, 0, ALL_TILES, xa, hb)
        outs["y"] = io.output("y", [D, TT])
        fin = [G["vec"][:, 4, c:c + 1] for c in range(8)]
        stage_norm(P, io, G, "final", xa, ALL_TILES, lambda ic: fin, lambda ic: None,
                   lambda c0, tw, ic: fm(outs["y"][:, c0:c0 + tw]), F32)
    if stages in ("all", "l0", "l1pre"):
        hp = io.scratch("hp", [D, 4608], F32)
        S = rw_scratch(io)
        with P.phase("zpad"):
            z = P.sb([128, 8, 64], F32)
            P.memset("vector", z[:], 0.0, ["z"])
            for k, o in enumerate((0, 64 + T, 4224, 4288 + C)):
                P.dma("sync", fm(hp[:, o:o + 64]), z[:], reads=["z"], writes=[("hpz", k)], sem=f"z{k}")

        def hdst(c0, tw, ic):
            o = 4288 if ic else 64 + c0
            return fm(hp[:, o:o + tw])

        def hbdst(c0, tw, ic):
            return fm(hb[:, c0:c0 + tw])

        def ms(l, kind, which):
            return lambda ic: mod_scalars(G, l, kind, ic)[which]

        stage_norm(P, io, G, "n_mix0", io["xin"], ALL_TILES, ms(0, 0, 0), ms(0, 0, 1), hdst, F32)
        stage_rwkv1a(P, io, G, hp, S)
        stage_rwkv1b(P, io, G, S)
        stage_rwkv2(P, io, G, S, io["xin"], xa)
        stage_norm(P, io, G, "n_mlp0", xa, ALL_TILES, ms(0, 1, 0), ms(0, 1, 1), hbdst, BF16)
        stage_mlp(P, io, G, 0, ALL_TILES, xa, hb)
        if stages == "l0":
            outs["y"] = io.output("y", [D, TT])
            with P.phase("copyout"):
                P.dma("sync", outs["y"], xa, writes=["o"], sem="cpa")
        else:
            stage_norm(P, io, G, "n_mix1", xa, ALL_TILES, ms(1, 0, 0), ms(1, 0, 1), hbdst, BF16)
            with P.scope():
                QT = io.scratch("qtd", [D, T], BF16)
                Kz = [P.ssb([128, 4, TT], BF16, f"Kz{k}") for k in range(2)]
                VA = P.ssb([128, 34, 390], BF16, "VA")
                stage_qkv(P, io, G, hb, QT, Kz, VA)
                stage_attn(P, io, G, QT, Kz, VA, xa)
            if stages == "l1pre":
                outs["y"] = io.output("y", [D, TT])
                with P.phase("copyout"):
                    P.dma("sync", outs["y"], xa, writes=["o"], sem="cpa")
            else:
                stage_norm(P, io, G, "n_mlp1", xa, LAT_TILES, ms(1, 1, 0), ms(1, 1, 1), hbdst, BF16)
                stage_mlp(P, io, G, 1, LAT_TILES, xa, hb)
                outs["y"] = io.output("y", [D, T])
                fin = [G["vec"][:, 4, c:c + 1] for c in range(8)]
                stage_norm(P, io, G, "final", xa, LAT_TILES, lambda ic: fin, lambda ic: None,
                           lambda c0, tw, ic: fm(outs["y"][:, c0:c0 + tw]), F32)
    if stages == "t_rwkv":
        hp = io.scratch("hp", [D, 4608], F32)
        S = rw_scratch(io)
        with P.phase("zpad"):
            z = P.sb([128, 8, 64], F32)
            P.memset("vector", z[:], 0.0, ["z"])
            for k, o in enumerate((0, 64 + T, 4224, 4288 + C)):
                P.dma("sync", fm(hp[:, o:o + 64]), z[:], reads=["z"], writes=[("hpz", k)], sem=f"z{k}")
        def hdst(c0, tw, ic):
            o = 4288 if ic else 64 + c0
            return fm(hp[:, o:o + tw])
        stage_norm(P, io, G, "n_mix0", io["xin"], ALL_TILES,
                   lambda ic: mod_scalars(G, 0, 0, ic)[0], lambda ic: mod_scalars(G, 0, 0, ic)[1], hdst, F32)
        stage_rwkv1(P, io, G, hp, S)
        stage_rwkv2(P, io, G, S, io["xin"], xa)
        outs["y"] = io.output("y", [D, TT])
        with P.phase("copyout"):
            P.dma("sync", outs["y"], xa, writes=["o"], sem="cpa")
    P.close()
    return nc, io.used, list(outs.keys()), P


def fmv(v):
    return np.ascontiguousarray(np.asarray(v, np.float32).reshape(8, 128).T)


def host_consts():
    c = {}
    c["c_ident"] = np.eye(128, dtype=np.float32)
    c["c_ones"] = np.ones((128, 128), np.float32)
    blk = np.zeros((128, 128), np.float32)
    blk[:64, :64] = 1
    blk[64:, 64:] = 1
    c["c_bones"] = blk
    i = np.arange(64)
    us = (i[:, None] < i[None, :]).astype(np.float32)
    ui = (i[:, None] <= i[None, :]).astype(np.float32)
    m = np.zeros((128, 4, 128), np.float32)
    for k, mk in enumerate([us, ui, us.T, ui.T]):
        m[:64, k, :64] = mk
        m[64:, k, 64:] = mk
    c["c_masks"] = m
    Pm = np.zeros((128, 128), np.float32)
    for d in range(128):
        if d % 32 < 16:
            Pm[d, d + 16] = -1.0
        else:
            Pm[d, d - 16] = 1.0
    c["c_perm"] = np.ascontiguousarray(Pm.T)
    sel = np.zeros((128, 2, 128), np.float32)
    sel[64, 0, :] = 1.0
    sel[63, 1, :] = 1.0
    c["c_sel"] = sel
    rm = np.ones((128, 256), np.float32)
    rm[:, ::64] = 0
    c["c_rmask"] = rm
    t = np.arange(T)
    row = (t // 64).astype(np.float32)
    col = (t % 64).astype(np.float32)
    freqs = (np.float32(10000.0) ** (-np.arange(0, 32, 2, dtype=np.float32) / np.float32(32))).astype(np.float32)
    ang = np.zeros((64, T), np.float32)
    for d in range(64):
        pos = row if d < 32 else col
        ang[d] = pos * freqs[d % 16]
    c["cosT"] = np.ascontiguousarray(np.concatenate([np.cos(ang), np.cos(ang)], 0).astype(np.float32))
    c["sinT"] = np.ascontiguousarray(np.concatenate([np.sin(ang), np.sin(ang)], 0).astype(np.float32))
    return c


def host_inputs(inp, b):
    f = lambda k: np.asarray(inp[k], np.float32)
    d = {}
    d["xin"] = np.ascontiguousarray(np.concatenate([f("x")[b].T, f("ctx")[b].T], axis=1))
    d["cvec"] = np.ascontiguousarray(np.stack([fmv(f("c")[b]), fmv(f("c_ctx"))], axis=-1))
    return d


def host_shared(inp):
    f = lambda k: np.asarray(inp[k], np.float32)
    s = dict(host_consts())
    s["w_mod"] = f("w_mod")
    s["b_mod"] = f("b_mod")
    vl = [f("norm_mix")[0], f("norm_mix")[1], f("norm_mlp")[0], f("norm_mlp")[1], f("final_norm")]
    vl += [f("rwkv_mu")[0, j] for j in range(6)]
    vl += [f("rwkv_w0")[0, 0], f("rwkv_w0")[0, 1], f("rwkv_a0")[0, 0], f("rwkv_a0")[0, 1]]
    vl += [f("rwkv_k_k")[0], f("rwkv_k_a")[0], np.zeros(D, np.float32), f("rwkv_r_k")[0].reshape(-1)]
    vl += [np.tile(f("attn_q_norm")[0], 16), np.tile(f("attn_k_norm")[0], 16)]
    assert len(vl) == NV
    s["vecs"] = np.ascontiguousarray(np.stack([fmv(v) for v in vl], axis=1))
    s["mlp_w1"] = f("mlp_w1")
    s["mlp_w2"] = f("mlp_w2")
    for k in ("wr", "wk", "wv", "wo", "w1", "w2", "a1", "a2", "g1", "g2"):
        s["rwkv_" + k] = f("rwkv_" + k)[0]
    lw = f("rwkv_ln_w")[0].reshape(8, 2, 64)
    lb = f("rwkv_ln_b")[0].reshape(8, 2, 64)
    s["lnw_st"] = np.ascontiguousarray(np.repeat(lw.transpose(1, 0, 2), 64, axis=0))
    s["lnb_st"] = np.ascontiguousarray(np.repeat(lb.transpose(1, 0, 2), 64, axis=0))
    wqkv = f("attn_wqkv")[0]
    s["attn_wq"] = np.ascontiguousarray(wqkv[:, :1024])
    wk = wqkv[:, 1024:1280].reshape(D, 4, 64)
    s["attn_wkd"] = np.ascontiguousarray(np.concatenate([wk, wk], axis=2).reshape(D, 512))
    s["attn_wv"] = np.ascontiguousarray(wqkv[:, 1280:1536])
    s["attn_wo"] = f("attn_wo")[0]
    return s


_CACHE = {}


def kernel(**inputs):
    if "prog" not in _CACHE:
        _CACHE["prog"] = build("all")
    nc, used, outnames, _ = _CACHE["prog"]
    shared = host_shared(inputs)
    in_maps = []
    for b in range(NCORES):
        hi = host_inputs(inputs, b)
        hi.update(shared)
        in_maps.append({k: hi[k] for k in used})
    res = run_bass_kernel_spmd(nc, in_maps, core_ids=list(range(NCORES)))
    out = np.stack([np.ascontiguousarray(res.results[b]["y"].T) for b in range(NCORES)], axis=0)
    return out.astype(np.float32)
```

```python
from contextlib import ExitStack, contextmanager
import re as re_mod
import numpy as np
import concourse.bass as bass
import concourse.mybir as mybir
from concourse.bass_utils import run_bass_kernel_spmd

F32 = mybir.dt.float32
BF16 = mybir.dt.bfloat16
AF = mybir.ActivationFunctionType
ALU = mybir.AluOpType
AX = mybir.AxisListType

D = 1024
T = 4096
C = 256
TT = T + C
NCORES = 8
C0 = float(np.exp(-0.5))
NV = 21
ENGS = ("tensor", "vector", "scalar", "gpsimd", "sync")


class Prog:
    def __init__(self, nc):
        self.nc = nc
        self.ges = ExitStack()
        self.sems = {}
        self.cnt = {}
        self.dpool = {False: [], True: []}
        self.seen = {e: {} for e in ENGS}
        self.n = 0
        self.pes = None
        self.total_ops = 0

    def _alloc(self, es, fn, shape, dtype, name):
        self.n += 1
        return es.enter_context(fn(name or f"t{self.n}", list(shape), dtype))

    def gsb(self, shape, dtype, name=None):
        return self._alloc(self.ges, self.nc.sbuf_tensor, shape, dtype, name)

    def sb(self, shape, dtype, name=None):
        return self._alloc(self.pes, self.nc.sbuf_tensor, shape, dtype, name)

    @contextmanager
    def scope(self):
        self.ses = ExitStack()
        yield self
        self.ses.close()
        self.ses = None

    def ssb(self, shape, dtype, name=None):
        return self._alloc(self.ses, self.nc.sbuf_tensor, shape, dtype, name)

    def ps(self, shape, dtype, name=None):
        return self._alloc(self.pes, self.nc.psum_tensor, shape, dtype, name)

    @contextmanager
    def phase(self, name):
        self.ops = []
        self.last_w = {}
        self.readers = {}
        self.last_dma = {}
        self.pes = ExitStack()
        self.pname = name
        yield self
        self._emit()
        self.pes.close()
        self.pes = None

    _PSUM_RE = re_mod.compile(r"^(pp|pa|pb|pq|pf|ps\w*|pX|py|pS|ptr|pw)\d*$")

    ns = None
    ns_set = frozenset()

    def _deps(self, reads, writes):
        if self.ns is not None:
            reads = tuple((r, self.ns) if r in self.ns_set else r for r in reads)
            writes = tuple((w, self.ns) if w in self.ns_set else w for w in writes)
        extra = tuple(r for r in reads if isinstance(r, str) and self._PSUM_RE.match(r) and r not in writes)
        if extra:
            writes = tuple(writes) + extra
        deps = {}
        for r in reads:
            if r in self.last_w:
                deps.setdefault(self.last_w[r], set()).add("RAW")
        for w in writes:
            if w in self.last_w:
                deps.setdefault(self.last_w[w], set()).add("WAW")
            for rd in self.readers.get(w, ()):
                deps.setdefault(rd, set()).add("WAR")
        idx = len(self.ops)
        for r in reads:
            self.readers.setdefault(r, []).append(idx)
        for w in writes:
            self.last_w[w] = idx
            self.readers[w] = []
        return deps

    def op(self, eng, fn, reads=(), writes=()):
        deps = self._deps(tuple(reads), tuple(writes))
        self.ops.append(dict(eng=eng, fn=fn, deps=deps, dma=None))
        return len(self.ops) - 1

    def dma(self, queue, out, in_, reads=(), writes=(), sem=None):
        deps = self._deps(tuple(reads), tuple(writes))
        prev = self.last_dma.get(sem)
        if prev is not None:
            deps.setdefault(prev, set()).add("SER")
        idx = len(self.ops)
        self.last_dma[sem] = idx
        self.ops.append(dict(eng=queue, fn=lambda e: e.dma_start(out=out, in_=in_), deps=deps, dma=sem))
        return idx

    def _emit(self):
        nc = self.nc
        ops = self.ops
        if self.last_dma:
            ops.append(dict(eng="sync", fn=None, deps={i: {"FIN"} for i in self.last_dma.values()}, dma=None))
        self.total_ops += len(ops)

        def needs_wait(x, d, kinds):
            if d["dma"] is not None or x["dma"] is not None:
                return True
            if d["eng"] != x["eng"]:
                return True
            if x["eng"] == "tensor":
                return False
            return bool(kinds & {"RAW", "FIN"})

        signal = [False] * len(ops)
        for x in ops:
            for di, kinds in x["deps"].items():
                d = ops[di]
                if d["dma"] is None and needs_wait(x, d, kinds):
                    signal[di] = True
        dkeys = {}
        nk = {False: 0, True: 0}
        for o in ops:
            if o["dma"] is not None and o["dma"] not in dkeys:
                sw = o["eng"] == "gpsimd"
                dkeys[o["dma"]] = (sw, nk[sw])
                nk[sw] += 1
        for sw in (False, True):
            while len(self.dpool[sw]) < nk[sw]:
                h = self.ges.enter_context(nc.semaphore(f"dq{int(sw)}_{len(self.dpool[sw])}"))
                self.dpool[sw].append([h, 0])
        for e in ENGS:
            if e not in self.sems:
                self.sems[e] = self.ges.enter_context(nc.semaphore(f"e_{e}"))
        token = [None] * len(ops)
        for i, o in enumerate(ops):
            if o["dma"] is not None:
                dk = dkeys[o["dma"]]
                slot = self.dpool[dk[0]][dk[1]]
                slot[1] += 16
                token[i] = (("d", dk), slot[1])
            elif signal[i]:
                self.cnt[o["eng"]] = self.cnt.get(o["eng"], 0) + 1
                token[i] = (("e", o["eng"]), self.cnt[o["eng"]])
        per_eng = {e: [] for e in ENGS}
        for i, o in enumerate(ops):
            per_eng[o["eng"]].append(i)

        def semh(key):
            return self.dpool[key[1][0]][key[1][1]][0] if key[0] == "d" else self.sems[key[1]]

        def run(engname, eng):
            seen = self.seen[engname]
            for i in per_eng[engname]:
                o = ops[i]
                waits = {}
                for di, kinds in o["deps"].items():
                    d = ops[di]
                    if not needs_wait(o, d, kinds):
                        continue
                    key, val = token[di]
                    if waits.get(key, 0) < val:
                        waits[key] = val
                for key, val in waits.items():
                    if seen.get(key, 0) >= val:
                        continue
                    seen[key] = val
                    eng.wait_ge(semh(key), val)
                if o["fn"] is None:
                    continue
                ins = o["fn"](eng)
                if o["dma"] is not None:
                    ins.then_inc(semh(token[i][0]), 16)
                elif signal[i]:
                    ins.then_inc(self.sems[engname], 1)

        with nc.Block() as block:
            @block.sync
            def _(e):
                run("sync", e)

            @block.tensor
            def _(e):
                run("tensor", e)

            @block.vector
            def _(e):
                run("vector", e)

            @block.scalar
            def _(e):
                run("scalar", e)

            @block.gpsimd
            def _(e):
                run("gpsimd", e)

    def close(self):
        self.ges.close()

    def mm(self, out, lhsT, rhs, start, stop, r, w):
        self.op("tensor", lambda e: e.matmul(out, lhsT=lhsT, rhs=rhs, start=start, stop=stop), r, w)

    def tr(self, out, in_, ident, r, w):
        self.op("tensor", lambda e: e.transpose(out, in_, ident), r, w)

    def tt(self, eng, out, in0, in1, op, r, w):
        self.op(eng, lambda e: e.tensor_tensor(out=out, in0=in0, in1=in1, op=op), r, w)

    def ts(self, eng, out, in0, s1, s2, op0, op1, r, w):
        if op1 is None:
            self.op(eng, lambda e: e.tensor_scalar(out=out, in0=in0, scalar1=s1, scalar2=None, op0=op0), r, w)
        else:
            self.op(eng, lambda e: e.tensor_scalar(out=out, in0=in0, scalar1=s1, scalar2=s2, op0=op0, op1=op1), r, w)

    def stt(self, out, in0, scalar, in1, op0, op1, r, w):
        self.op("vector", lambda e: e.scalar_tensor_tensor(out=out, in0=in0, scalar=scalar, in1=in1, op0=op0, op1=op1), r, w)

    def act(self, out, in_, func, r, w, bias=None, scale=None):
        kw = {}
        if bias is not None:
            kw["bias"] = bias
        if scale is not None:
            kw["scale"] = scale
        self.op("scalar", lambda e: e.activation(out=out, in_=in_, func=func, **kw), r, w)

    def cp(self, eng, out, in_, r, w):
        if eng == "scalar":
            self.op(eng, lambda e: e.activation(out=out, in_=in_, func=AF.Copy), r, w)
        else:
            self.op(eng, lambda e: e.tensor_copy(out=out, in_=in_), r, w)

    def memset(self, eng, ap, val, w):
        self.op(eng, lambda e: e.memset(ap, val), (), w)


def fm(ap2d):
    return ap2d.rearrange("(c p) n -> p c n", p=128)


LAT_TILES = [(i * 512, 512, False) for i in range(8)]
ALL_TILES = LAT_TILES + [(T, 256, True)]


def stage_init(P, io, G):
    nc = P.nc
    G["identf"] = P.gsb([128, 128], F32, "identf")
    G["identb"] = P.gsb([128, 128], BF16, "identb")
    G["onesb"] = P.gsb([128, 128], BF16, "onesb")
    G["bones"] = P.gsb([128, 128], BF16, "bones")
    G["masks"] = P.gsb([128, 4, 128], BF16, "masks")
    G["perm"] = P.gsb([128, 128], BF16, "perm")
    G["rmask"] = P.gsb([128, 256], F32, "rmask")
    G["vec"] = P.gsb([128, NV, 8], F32, "vec")
    G["modv"] = P.gsb([128, 2, 6, 8, 2], F32, "modv")
    G["gg"] = P.gsb([128, 2, 2, 8, 2], F32, "gg")
    with P.phase("init"):
        P.dma("sync", G["identf"][:], io["c_ident"], writes=["identf"], sem="identf")
        P.dma("sync", G["rmask"][:], io["c_rmask"], writes=["rmask"], sem="rmask")
        P.dma("sync", G["vec"][:], io["vecs"], writes=["vec"], sem="vec")
        P.dma("gpsimd", G["identb"][:], io["c_ident"], writes=["identb"], sem="identb")
        P.dma("gpsimd", G["onesb"][:], io["c_ones"], writes=["onesb"], sem="onesb")
        P.dma("gpsimd", G["bones"][:], io["c_bones"], writes=["bones"], sem="bones")
        P.dma("gpsimd", G["masks"][:], io["c_masks"], writes=["masks"], sem="masks")
        P.dma("gpsimd", G["perm"][:], io["c_perm"], writes=["perm"], sem="perm")
        vec = G["vec"]
        P.ts("vector", vec[:, 17, :], vec[:, 16, :], -1.0, 1.0, ALU.mult, ALU.add, ["vec"], ["vec"])
        sv = P.sb([128, 8, 2], F32)
        svs = P.sb([128, 8, 2], F32)
        P.dma("sync", sv[:], io["cvec"], writes=["sv"], sem="sv")
        P.act(svs[:], sv[:], AF.Silu, ["sv"], ["svs"])
        brow = P.sb([2, 2 * 6144], F32)
        row = P.sb([2, 2 * 6144], F32)
        P.dma("sync", brow[:], io["b_mod"].rearrange("l n -> (l n)").partition_broadcast(2), writes=["brow"], sem="brow")
        wt = [P.sb([128, 8, 512], F32) for _ in range(2)]
        psr = [P.ps([128, 512], F32) for _ in range(2)]
        pst = P.ps([128, 512], F32)
        k = 0
        for l in range(2):
            for nb in range(12):
                b = k % 2
                k += 1
                P.dma("sync", wt[b][:], fm(io["w_mod"][l, :, nb * 512:(nb + 1) * 512]), writes=[f"wt{b}"], sem=f"wt{b}")
                for c in range(8):
                    P.mm(psr[b][0:2, :], svs[:, c, :], wt[b][:, c, :], c == 0, c == 7, ["svs", f"wt{b}"], [f"psr{b}"])
                o = l * 6144 + nb * 512
                P.tt("vector", row[:, o:o + 512], psr[b][0:2, :], brow[:, o:o + 512], ALU.add, [f"psr{b}", "brow"], ["row"])
        for l in range(2):
            for blk in range(48):
                o = l * 6144 + blk * 128
                P.tr(pst[:, l * 96 + blk * 2:l * 96 + blk * 2 + 2], row[0:2, o:o + 128], G["identf"][0:2, 0:2], ["row", "identf"], ["pst"])
        P.cp("vector", G["modv"][:].rearrange("p l m c j -> p (l m c j)"), pst[:, 0:192], ["pst"], ["modv"])
        modv, gg = G["modv"], G["gg"]
        for l in range(2):
            for kind in range(2):
                sc = modv[:, l, 1 + 3 * kind, :, :]
                nv = vec[:, (0 if kind == 0 else 2) + l, :].unsqueeze(2).broadcast_to([128, 8, 2])
                P.ts("vector", gg[:, l, kind, :, :], sc, 1.0, None, ALU.add, None, ["modv"], ["gg"])
                P.tt("vector", gg[:, l, kind, :, :], gg[:, l, kind, :, :], nv, ALU.mult, ["gg", "vec"], ["gg"])


def mod_scalars(G, l, kind, isctx):
    j = 1 if isctx else 0
    gains = [G["gg"][:, l, kind, c, j:j + 1] for c in range(8)]
    shifts = [G["modv"][:, l, 3 * kind, c, j:j + 1] for c in range(8)]
    gates = [G["modv"][:, l, 3 * kind + 2, c, j:j + 1] for c in range(8)]
    return gains, shifts, gates


def stage_norm(P, io, G, name, src, tiles, gains_fn, shifts_fn, dst_fn, out_dtype):
    with P.phase(name):
        xt = [P.sb([128, 8, 512], F32) for _ in range(2)]
        sq = P.sb([128, 8, 512], BF16)
        lnv = P.sb([128, 512], F32)
        rstd = P.sb([128, 512], F32)
        tmp = [P.sb([128, 512], F32) for _ in range(2)]
        ho = [P.sb([128, 8, 512], out_dtype) for _ in range(2)]
        ps = [P.ps([128, 512], F32) for _ in range(2)]

        def load(i):
            c0, tw, _ = tiles[i]
            b = i % 2
            P.dma("sync", xt[b][:, :, :tw], fm(src[:, c0:c0 + tw]), writes=[f"xt{b}"], sem=f"xt{b}")

        load(0)
        for i, (c0, tw, isctx) in enumerate(tiles):
            b = i % 2
            if i + 1 < len(tiles):
                load(i + 1)
            gains = gains_fn(isctx)
            shifts = shifts_fn(isctx)
            P.act(sq[:, :, :tw], xt[b][:, :, :tw], AF.Square, [f"xt{b}"], ["sq"])
            for c in range(8):
                P.mm(ps[b][:, :tw], G["onesb"][:], sq[:, c, :tw], c == 0, c == 7, ["sq", "onesb"], [f"ps{b}"])
            P.act(lnv[:, :tw], ps[b][:, :tw], AF.Ln, [f"ps{b}"], ["lnv"], bias=1e-6, scale=1.0 / D)
            P.act(rstd[:, :tw], lnv[:, :tw], AF.Exp, ["lnv"], ["rstd"], scale=-0.5)
            for c in range(8):
                if shifts is None:
                    P.stt(ho[b][:, c, :tw], xt[b][:, c, :tw], gains[c], rstd[:, :tw], ALU.mult, ALU.mult,
                          [f"xt{b}", "rstd", "vec", "gg"], [f"ho{b}"])
                else:
                    t = tmp[c % 2]
                    P.stt(t[:, :tw], xt[b][:, c, :tw], gains[c], rstd[:, :tw], ALU.mult, ALU.mult,
                          [f"xt{b}", "rstd", "vec", "gg"], [f"tmp{c % 2}"])
                    P.act(ho[b][:, c, :tw], t[:, :tw], AF.Identity, [f"tmp{c % 2}", "modv"], [f"ho{b}"], bias=shifts[c])
            P.dma("gpsimd", dst_fn(c0, tw, isctx), ho[b][:, :, :tw], reads=[f"ho{b}"], writes=[("dst", i)], sem=f"ho{b}")


def stage_mlp(P, io, G, l, tiles, xa, hb):
    for half in range(2):
        with P.phase(f"mlp{l}{half}"):
            w1 = P.sb([128, 8, 2048], BF16)
            w2 = P.sb([128, 16, 1024], BF16)
            for q in range(2):
                P.dma("gpsimd", w1[:, :, q * 1024:(q + 1) * 1024],
                      fm(io["mlp_w1"][l, :, half * 2048 + q * 1024: half * 2048 + (q + 1) * 1024]), writes=["w1"], sem=f"w1{q}")
                P.dma("gpsimd", w2[:, q * 8:(q + 1) * 8, :],
                      io["mlp_w2"][l, half * 2048 + q * 1024: half * 2048 + (q + 1) * 1024, :].rearrange("(f p) n -> p f n", p=128),
                      writes=["w2"], sem=f"w2{q}")
            xt = [P.sb([128, 8, 512], F32) for _ in range(2)]
            ht = [P.sb([128, 8, 512], BF16) for _ in range(2)]
            h1 = P.sb([128, 16, 512], BF16)
            r1 = [P.sb([128, 512], F32) for _ in range(4)]
            ps = [P.ps([128, 512], F32) for _ in range(8)]

            def load(i):
                c0, tw, _ = tiles[i]
                b = i % 2
                P.dma("sync", ht[b][:, :, :tw], fm(hb[:, c0:c0 + tw]), writes=[f"ht{b}"], sem=f"ht{b}")
                P.dma("sync", xt[b][:, :, :tw], fm(xa[:, c0:c0 + tw]), reads=[("xa", i)], writes=[f"xt{b}"], sem=f"xt{b}")

            load(0)
            for i, (c0, tw, isctx) in enumerate(tiles):
                b = i % 2
                if i + 1 < len(tiles):
                    load(i + 1)
                _, _, gates = mod_scalars(G, l, 1, isctx)
                for fc in range(16):
                    pb = fc % 4
                    for c in range(8):
                        P.mm(ps[pb][:, :tw], w1[:, c, fc * 128:(fc + 1) * 128], ht[b][:, c, :tw], c == 0, c == 7,
                             ["w1", f"ht{b}"], [f"ps{pb}"])
                    P.act(r1[pb][:, :tw], ps[pb][:, :tw], AF.Relu, [f"ps{pb}"], [f"r1{pb}"])
                    P.tt("gpsimd", h1[:, fc, :tw], r1[pb][:, :tw], r1[pb][:, :tw], ALU.mult, [f"r1{pb}"], [("h1", fc)])
                for oc in range(8):
                    pb = 4 + oc % 4
                    for fc in range(16):
                        P.mm(ps[pb][:, :tw], w2[:, fc, oc * 128:(oc + 1) * 128], h1[:, fc, :tw], fc == 0, fc == 15,
                             ["w2", ("h1", fc)], [f"ps{pb}"])
                    P.stt(xt[b][:, oc, :tw], ps[pb][:, :tw], gates[oc], xt[b][:, oc, :tw], ALU.mult, ALU.add,
                          [f"ps{pb}", f"xt{b}", "modv"], [f"xt{b}"])
                P.dma("sync", fm(xa[:, c0:c0 + tw]), xt[b][:, :, :tw], reads=[f"xt{b}"], writes=[("xa", i)], sem=f"xt{b}")


RW_ORDER1 = [(True, 0)] + [(False, i) for i in range(16)]
RW_ORDER2 = [(True, 0)] + [(False, i) for i in range(15, -1, -1)]


def rw_scratch(io):
    S = {}
    S["yp"] = io.scratch("rw_yp", [17, 8, 128, 256], F32)
    S["sadd"] = io.scratch("rw_sadd", [17, 8, 128, 256], F32)
    S["vst"] = io.scratch("rw_vst", [17, 8, 128, 256], F32)
    S["gst"] = io.scratch("rw_gst", [17, 8, 128, 256], F32)
    S["gyb"] = io.scratch("rw_gyb", [17, 8, 128, 512], BF16)
    S["gsb"] = io.scratch("rw_gsb", [17, 8, 128, 512], BF16)
    S["gamb"] = io.scratch("rw_gamb", [17, 128, 32], F32)
    S["bon"] = io.scratch("rw_bon", [17, 128, 32], F32)
    S["ops"] = io.scratch("rw_ops", [17, 8, 128, 2048], BF16)
    S["vb"] = io.scratch("rw_vb", [17, 8, 128, 256], BF16)
    S["gam"] = io.scratch("rw_gam", [17, 8, 128, 8], F32)
    return S


def stage_rwkv1(P, io, G, hp, S, dbg=None):
    vec, masks, identb, identf, bones, onesb, rmask = (G[k] for k in ("vec", "masks", "identb", "identf", "bones", "onesb", "rmask"))
    with P.phase("rwkv1"):
        wr = P.sb([128, 8, 1024], BF16)
        wk = P.sb([128, 8, 1024], BF16)
        wv = P.sb([128, 8, 1024], BF16)
        for w, nm in ((wr, "rwkv_wr"), (wk, "rwkv_wk"), (wv, "rwkv_wv")):
            P.dma("gpsimd", w[:], fm(io[nm]), writes=[nm], sem=nm)
        lw1 = P.sb([128, 8, 128], BF16)
        la1 = P.sb([128, 8, 128], BF16)
        g1 = P.sb([128, 8, 128], BF16)
        for d in range(2):
            P.dma("gpsimd", lw1[:, :, d * 64:(d + 1) * 64], io["rwkv_w1"][d].rearrange("(c p) j -> p c j", p=128), writes=["lw1"], sem=f"lw1{d}")
            P.dma("gpsimd", la1[:, :, d * 64:(d + 1) * 64], io["rwkv_a1"][d].rearrange("(c p) j -> p c j", p=128), writes=["la1"], sem=f"la1{d}")
        P.dma("gpsimd", g1[:], io["rwkv_g1"].rearrange("(c p) j -> p c j", p=128), writes=["g1"], sem="g1")
        w2s = P.sb([128, 1024], BF16)
        a2s = P.sb([128, 1024], BF16)
        g2 = P.sb([128, 1024], BF16)
        P.dma("gpsimd", w2s[:], io["rwkv_w2"].rearrange("d j f -> (d j) f"), writes=["w2s"], sem="w2s")
        P.dma("gpsimd", a2s[:], io["rwkv_a2"].rearrange("d j f -> (d j) f"), writes=["a2s"], sem="a2s")
        P.dma("gpsimd", g2[:], io["rwkv_g2"], writes=["g2"], sem="g2")

        hh = P.sb([128, 8, 384], F32)
        xx = P.sb([128, 8, 256], F32)
        xr = P.sb([128, 8, 256], BF16)
        xk = P.sb([128, 8, 256], BF16)
        xv = P.sb([128, 8, 256], BF16)
        xrot = P.sb([128, 8, 256], BF16)
        lwt = P.sb([128, 256], BF16)
        lat = P.sb([128, 256], BF16)
        sg = P.sb([128, 256], BF16)
        f32t = {}
        for nm in ("r", "k", "sw0", "sw1", "ag0", "ag1", "kq", "lnv", "rs", "kkn", "fac", "kd0", "kd1", "b0", "b1",
                   "L", "Lx", "Lb", "E1", "E2", "E3", "ks"):
            f32t[nm] = P.sb([128, 256], F32, "t_" + nm)
        sqb = P.sb([128, 256], BF16)
        RK = P.sb([128, 4, 2, 64], BF16)
        VTbd = P.sb([128, 4, 128], F32)
        GTbd = P.sb([128, 4, 128], F32)
        Vf = P.sb([128, 4, 64], F32)
        Gf = P.sb([128, 4, 64], F32)
        YPs = P.sb([128, 4, 64], F32)
        SAs = P.sb([128, 4, 64], F32)
        gamb_t = P.sb([128, 8, 4], F32)
        bon_t = P.sb([128, 8, 4], F32)
        Sf = P.sb([128, 8, 64], BF16)
        ARq = [[P.sb([128, 4, 2, 128], BF16, f"AR{q}{d}") for d in range(2)] for q in range(2)]
        KTq = [[P.sb([128, 4, 128], BF16, f"KT{q}{d}") for d in range(2)] for q in range(2)]
        BTq = [[P.sb([128, 4, 128], BF16, f"BT{q}{d}") for d in range(2)] for q in range(2)]
        Vbq = [P.sb([128, 4, 64], BF16, f"Vb{q}") for q in range(3)]
        gamq = [[P.sb([128, 4], F32, f"gam{q}{d}") for d in range(2)] for q in range(3)]
        inv = []
        for d in range(2):
            st = {}
            for nm, shp in (("Atok", [128, 4, 128]), ("Btok", [128, 4, 128]), ("MQ", [128, 4, 256]), ("MWa", [128, 4, 2, 128]),
                            ("MWb", [128, 4, 2, 128]), ("MTa", [128, 4, 128]), ("MTb", [128, 4, 128])):
                st[nm] = P.sb(shp, BF16, f"i{d}_{nm}")
            inv.append(st)
        fin = []
        for q in range(2):
            row = []
            for d in range(2):
                st = {}
                for nm, shp in (("Ktok", [128, 4, 128]), ("NP", [128, 4, 256]), ("XW", [128, 4, 256]), ("NVb", [128, 4, 64]),
                                ("GY", [128, 4, 128]), ("GS", [128, 4, 128])):
                    st[nm] = P.sb(shp, BF16, f"f{q}{d}_{nm}")
                row.append(st)
            fin.append(row)
        ppt = [P.ps([128, 512], F32) for _ in range(2)]
        pp = [t_[:, 0:256] for t_ in ppt]
        pf = P.ps([128, 512], F32)
        pb = [P.ps([128, 512], F32) for _ in range(5)]
        cnt = {"pp": 0, "pb": 0}
        nmod = {"pp": 2, "pb": 5}

        def nxt(kind):
            i = cnt[kind] % nmod[kind]
            cnt[kind] += 1
            return i

        for q in range(2):
            for d in range(2):
                P.memset("gpsimd", ARq[q][d][:], 0.0, [f"AR{q}{d}"])
                P.memset("gpsimd", KTq[q][d][:], 0.0, [f"KT{q}{d}"])
                P.memset("gpsimd", BTq[q][d][:], 0.0, [f"BT{q}{d}"])
        P.memset("gpsimd", RK[:], 0.0, ["RK"])
        P.memset("gpsimd", VTbd[:], 0.0, ["VTbd"])
        P.memset("gpsimd", GTbd[:], 0.0, ["GTbd"])
        P.memset("gpsimd", Sf[:], 0.0, [("Sf", p) for p in range(8)])

        def v3(ap):
            return ap.rearrange("p (u s) -> p u s", s=64)

        def u128(ap):
            return ap.rearrange("p (u x) -> p u x", x=128)

        def load_hh(ti):
            isctx, idx = RW_ORDER1[ti]
            off = 4288 if isctx else 64 + 256 * idx
            P.dma("sync", hh[:], fm(hp[:, off - 64: off + 320]), writes=["hh"], sem="hh")

        def proj8(w_cols_fn, xb, bn, extra_r):
            i = nxt("pp")
            for c in range(8):
                P.mm(pp[i], w_cols_fn(c), xb[:, c, :], c == 0, c == 7, [(bn, c)] + extra_r, [f"pp{i}"])
            return i

        def tprep(ti):
            isctx, idx = RW_ORDER1[ti]
            hc = hh[:, :, 64:320]
            XXW = [("xx", c) for c in range(8)]
            if not isctx:
                h4 = hh[:, :, 64:320].rearrange("p c (r w) -> p c r w", w=64)
                x4 = xx[:].rearrange("p c (r w) -> p c r w", w=64)
                P.tt("vector", x4[:, 0:2, :, 1:64], h4[:, 0:2, :, 0:63], h4[:, 0:2, :, 1:64], ALU.subtract, ["hh"], XXW[0:2])
                P.ts("gpsimd", x4[:, 0:2, :, 0:1], h4[:, 0:2, :, 0:1], -1.0, 0.0, ALU.mult, ALU.add, ["hh"], [("xxe", 0)])
                P.tt("vector", x4[:, 2:4, :, 0:63], h4[:, 2:4, :, 1:64], h4[:, 2:4, :, 0:63], ALU.subtract, ["hh"], XXW[2:4])
                P.ts("gpsimd", x4[:, 2:4, :, 63:64], h4[:, 2:4, :, 63:64], -1.0, 0.0, ALU.mult, ALU.add, ["hh"], [("xxe", 1)])
                P.tt("gpsimd", xx[:, 4:6, :], hh[:, 4:6, 0:256], hh[:, 4:6, 64:320], ALU.subtract, ["hh"], XXW[4:6])
                P.tt("gpsimd", xx[:, 6:8, :], hh[:, 6:8, 128:384], hh[:, 6:8, 64:320], ALU.subtract, ["hh"], XXW[6:8])
            else:
                P.tt("vector", xx[:, 0:4, :], hh[:, 0:4, 63:319], hh[:, 0:4, 64:320], ALU.subtract, ["hh"], XXW[0:4] + [("xxe", 0)])
                P.tt("gpsimd", xx[:, 4:8, :], hh[:, 4:8, 65:321], hh[:, 4:8, 64:320], ALU.subtract, ["hh"], XXW[4:8] + [("xxe", 1)])
            yield

            def mk_xj(j, buf, bn):
                for c in range(8):
                    P.stt(buf[:, c, :], xx[:, c, :], vec[:, 5 + j, c:c + 1], hc[:, c, :], ALU.mult, ALU.add,
                          [("xx", c), ("xxe", 0), ("xxe", 1), "hh", "vec"], [(bn, c)])

            mk_xj(1, xrot, "xrot")
            yield
            i = proj8(lambda c: lw1[:, c, :], xrot, "xrot", ["lw1"])
            P.act(lwt[:], pp[i], AF.Tanh, [f"pp{i}"], ["lwt"])
            yield
            mk_xj(4, xrot, "xrot")
            yield
            i = proj8(lambda c: la1[:, c, :], xrot, "xrot", ["la1"])
            P.cp("scalar", lat[:], pp[i], [f"pp{i}"], ["lat"])
            yield
            mk_xj(5, xrot, "xrot")
            yield
            i = proj8(lambda c: g1[:, c, :], xrot, "xrot", ["g1"])
            P.act(sg[:], pp[i], AF.Sigmoid, [f"pp{i}"], ["sg"])
            yield
            mk_xj(0, xr, "xr")
            yield
            mk_xj(2, xk, "xk")
            yield
            mk_xj(3, xv, "xv")
            if ti + 1 < len(RW_ORDER1):
                load_hh(ti + 1)
            yield

        def prep(ti, oc, q, z):
            isctx, idx = RW_ORDER1[ti]
            tg = 16 if isctx else idx
            cs = slice(oc * 128, (oc + 1) * 128)
            t = f32t
            AR, KT, BT, Vb, gam = ARq[q], KTq[q], BTq[q], Vbq[z], gamq[z]
            i = proj8(lambda c: wr[:, c, cs], xr, "xr", ["rwkv_wr"])
            P.cp("scalar", t["r"][:], pp[i], [f"pp{i}"], ["r"])
            i = proj8(lambda c: wk[:, c, cs], xk, "xk", ["rwkv_wk"])
            P.cp("scalar", t["k"][:], pp[i], [f"pp{i}"], ["k"])
            i = proj8(lambda c: wv[:, c, cs], xv, "xv", ["rwkv_wv"])
            vt4 = VTbd[:].rearrange("p u (h s) -> p u h s", h=2)
            for h2 in range(2):
                sl = slice(h2 * 64, (h2 + 1) * 64)
                P.cp("scalar", vt4[sl, :, h2, :], v3(pp[i][sl, :]), [f"pp{i}"], ["VTbd"])
            i = nxt("pp")
            P.mm(pp[i], g2[:, cs], sg[:], True, True, ["g2", "sg"], [f"pp{i}"])
            gt4 = GTbd[:].rearrange("p u (h s) -> p u h s", h=2)
            for h2 in range(2):
                sl = slice(h2 * 64, (h2 + 1) * 64)
                P.cp("scalar", gt4[sl, :, h2, :], v3(pp[i][sl, :]), [f"pp{i}"], ["GTbd"])
            yield
            j = nxt("pb")
            for u in range(4):
                P.tr(pb[j][:, u * 128:(u + 1) * 128], VTbd[:, u, :], identf[:], ["VTbd", "identf"], [f"pb{j}"])
            pv = u128(pb[j][:])
            for h2 in range(2):
                sl = slice(h2 * 64, (h2 + 1) * 64)
                P.cp("scalar", Vf[sl, :, :], pv[sl, :, h2 * 64:(h2 + 1) * 64], [f"pb{j}"], ["Vf"])
            P.cp("gpsimd", Vb[:], Vf[:], ["Vf"], [f"Vb{z}"])
            P.dma("sync", S["vst"][tg, oc].rearrange("p (u s) -> p u s", s=64), Vf[:], reads=["Vf"], writes=[("vst", tg, oc)], sem="Vf")
            j = nxt("pb")
            for u in range(4):
                P.tr(pb[j][:, u * 128:(u + 1) * 128], GTbd[:, u, :], identf[:], ["GTbd", "identf"], [f"pb{j}"])
            pv = u128(pb[j][:])
            for h2 in range(2):
                sl = slice(h2 * 64, (h2 + 1) * 64)
                P.cp("scalar", Gf[sl, :, :], pv[sl, :, h2 * 64:(h2 + 1) * 64], [f"pb{j}"], ["Gf"])
            P.dma("sync", S["gst"][tg, oc].rearrange("p (u s) -> p u s", s=64), Gf[:], reads=["Gf"], writes=[("gst", tg, oc)], sem="Gf")
            yield
            for d in range(2):
                dl = slice(d * 64, (d + 1) * 64)
                i = nxt("pp")
                P.mm(pp[i], w2s[dl, cs], lwt[dl, :], True, True, ["w2s", "lwt"], [f"pp{i}"])
                P.act(t[f"sw{d}"][:], pp[i], AF.Sigmoid, [f"pp{i}", "vec"], [f"sw{d}"], bias=vec[:, 11 + d, oc:oc + 1])
                i = nxt("pp")
                P.mm(pp[i], a2s[dl, cs], lat[dl, :], True, True, ["a2s", "lat"], [f"pp{i}"])
                P.act(t[f"ag{d}"][:], pp[i], AF.Sigmoid, [f"pp{i}", "vec"], [f"ag{d}"], bias=vec[:, 13 + d, oc:oc + 1])
            yield
            P.ts("vector", t["kq"][:], t["k"][:], vec[:, 15, oc:oc + 1], None, ALU.mult, None, ["k", "vec"], ["kq"])
            P.act(sqb[:], t["kq"][:], AF.Square, ["kq"], ["sqb"])
            i = nxt("pp")
            P.mm(pp[i], bones[:], sqb[:], True, True, ["bones", "sqb"], [f"pp{i}"])
            P.act(t["lnv"][:], pp[i], AF.Ln, [f"pp{i}"], ["lnv"], bias=1e-12)
            P.act(t["rs"][:], t["lnv"][:], AF.Exp, ["lnv"], ["rs"], scale=-0.5)
            P.tt("gpsimd", t["kkn"][:], t["kq"][:], t["rs"][:], ALU.mult, ["kq", "rs"], ["kkn"])
            for d in range(2):
                sw, ag, kd, bb = t[f"sw{d}"], t[f"ag{d}"], t[f"kd{d}"], t[f"b{d}"]
                EE = "gpsimd" if d == 0 else "vector"
                P.ts(EE, t["fac"][:], ag[:], vec[:, 16, oc:oc + 1], vec[:, 17, oc:oc + 1], ALU.mult, ALU.add, [f"ag{d}", "vec"], ["fac"])
                P.tt(EE, kd[:], t["k"][:], t["fac"][:], ALU.mult, ["k", "fac"], [f"kd{d}"])
                P.tt(EE, bb[:], t["kkn"][:], ag[:], ALU.mult, ["kkn", f"ag{d}"], [f"b{d}"])
                P.op("vector", lambda e, sw=sw: e.tensor_tensor_scan(out=t["L"][:], data0=rmask[:], data1=sw[:], initial=0.0,
                                                                      op0=ALU.mult, op1=ALU.add), [f"sw{d}", "rmask"], ["L"])
                L3 = v3(t["L"][:])
                if d == 0:
                    P.tt(EE, t["Lx"][:], t["L"][:], sw[:], ALU.subtract, ["L", f"sw{d}"], ["Lx"])
                    Li, Lin = t["L"], "L"
                else:
                    P.tt(EE, v3(t["Lx"][:]), L3[:, :, 63:64].broadcast_to([128, 4, 64]), L3, ALU.subtract, ["L"], ["Lx"])
                    P.tt(EE, t["Lb"][:], t["Lx"][:], sw[:], ALU.add, ["Lx", f"sw{d}"], ["Lb"])
                    Li, Lin = t["Lb"], "Lb"
                P.act(t["E1"][:], Li[:], AF.Exp, [Lin], ["E1"], scale=-C0)
                P.act(t["E3"][:], Li[:], AF.Exp, [Lin], ["E3"], scale=C0)
                P.act(t["E2"][:], t["Lx"][:], AF.Exp, ["Lx"], ["E2"], scale=-C0)
                ar5 = AR[d][:].rearrange("p u a (h s) -> p u a h s", h=2)
                kt4 = KT[d][:].rearrange("p u (h s) -> p u h s", h=2)
                bt4 = BT[d][:].rearrange("p u (h s) -> p u h s", h=2)
                for h2 in range(2):
                    sl = slice(h2 * 64, (h2 + 1) * 64)
                    P.stt(ar5[sl, :, 0, h2, :], v3(t["kkn"][sl, :]), -1.0, v3(t["E2"][sl, :]), ALU.mult, ALU.mult, ["kkn", "E2"], [f"AR{q}{d}"])
                    P.tt(EE, ar5[sl, :, 1, h2, :], v3(t["r"][sl, :]), v3(t["E1"][sl, :]), ALU.mult, ["r", "E1"], [f"AR{q}{d}"])
                    P.tt(EE, kt4[sl, :, h2, :], v3(kd[sl, :]), v3(t["E3"][sl, :]), ALU.mult, [f"kd{d}", "E3"], [f"KT{q}{d}"])
                    P.tt(EE, bt4[sl, :, h2, :], v3(bb[sl, :]), v3(t["E3"][sl, :]), ALU.mult, [f"b{d}", "E3"], [f"BT{q}{d}"])
                E13 = v3(t["E1"][:])
                gsrc = E13[:, :, 63] if d == 0 else E13[:, :, 0]
                P.cp("vector", gam[d][:], gsrc, ["E1"], [f"gam{z}{d}"])
                if d == 1:
                    P.cp("gpsimd", gamb_t[:, oc, :], gam[1][:], [f"gam{z}1"], ["gamb_t"])
                yield
            P.tt("gpsimd", t["ks"][:], t["kd0"][:], t["kd1"][:], ALU.add, ["kd0", "kd1"], ["ks"])
            for h2 in range(2):
                sl = slice(h2 * 64, (h2 + 1) * 64)
                P.stt(RK[sl, :, h2, :], v3(t["r"][sl, :]), vec[sl, 18, oc:oc + 1], v3(t["ks"][sl, :]), ALU.mult, ALU.mult, ["r", "ks", "vec"], ["RK"])
            i = nxt("pp")
            for u in range(4):
                P.mm(pp[i][:, u:u + 1], RK[:, u, :, :].rearrange("p h s -> p (h s)"), onesb[:, 0:1], True, True, ["RK", "onesb"], [f"pp{i}"])
            P.cp("scalar", bon_t[:, oc, :], pp[i][:, 0:4], [f"pp{i}"], ["bon_t"])
            if oc == 7:
                P.dma("sync", S["gamb"][tg], gamb_t[:].rearrange("p a b -> p (a b)"), reads=["gamb_t"], writes=[("gamb", tg)], sem="gamb_t")
                P.dma("sync", S["bon"][tg], bon_t[:].rearrange("p a b -> p (a b)"), reads=["bon_t"], writes=[("bon", tg)], sem="bon_t")
            yield

        def chain(ti, oc, q, d, z):
            AR, KT, BT, Vb = ARq[q][d], KTq[q][d], BTq[q][d], Vbq[z]
            ARn, KTn, BTn, Vbn = f"AR{q}{d}", f"KT{q}{d}", f"BT{q}{d}", f"Vb{z}"
            iv, fn = inv[d], fin[q][d]
            IR = lambda nm: f"i{d}_{nm}"
            FR = lambda nm: f"f{q}{d}_{nm}"
            mS, mC = (0, 2) if d == 0 else (2, 0)
            mSI = masks[:, mS:mS + 2, :].rearrange("p a b -> p (a b)").unsqueeze(1).broadcast_to([128, 4, 256])
            mCb = masks[:, mC, :].unsqueeze(1).broadcast_to([128, 4, 128])
            idb = identb[:].unsqueeze(1).broadcast_to([128, 4, 128])
            for src, srcn, dst, dstn in ((AR[:, :, 0, :], ARn, iv["Atok"], IR("Atok")), (BT[:], BTn, iv["Btok"], IR("Btok")),
                                         (KT[:], KTn, fn["Ktok"], FR("Ktok"))):
                j = nxt("pb")
                pbt = pb[j][:].bitcast(BF16)
                for u in range(4):
                    P.tr(pbt[:, u * 128:(u + 1) * 128], src[:, u, :], identb[:], [srcn, "identb"], [f"pb{j}"])
                P.cp("scalar", dst[:].rearrange("p u x -> p (u x)"), pbt[:, 0:512], [f"pb{j}"], [dstn])
            mSb = masks[:, mS, :].unsqueeze(1).broadcast_to([128, 4, 128])
            mIb = masks[:, mS + 1, :].unsqueeze(1).broadcast_to([128, 4, 128])

            def two_bank(mm_fn):
                j0, j1 = nxt("pb"), nxt("pb")
                for u in range(4):
                    mm_fn(u, pb[j0][:, u * 128:(u + 1) * 128], f"pb{j0}", pb[j1][:, u * 128:(u + 1) * 128], f"pb{j1}")
                return j0, j1

            for lhs, lhsn, dst, dstn in ((BT, BTn, iv["MQ"], IR("MQ")), (KT, KTn, fn["NP"], FR("NP"))):
                def mm_ab(u, o0, n0, o1, n1, lhs=lhs, lhsn=lhsn):
                    P.mm(o0, lhs[:, u, :], AR[:, u, 0, :], True, True, [lhsn, ARn], [n0])
                    P.mm(o1, lhs[:, u, :], AR[:, u, 1, :], True, True, [lhsn, ARn], [n1])
                j0, j1 = two_bank(mm_ab)
                P.tt("vector", dst[:, :, 0:128], u128(pb[j0][:]), mSb, ALU.mult, [f"pb{j0}", "masks"], [dstn])
                P.tt("vector", dst[:, :, 128:256], u128(pb[j1][:]), mIb, ALU.mult, [f"pb{j1}", "masks"], [dstn])
            j = nxt("pb")
            for u in range(4):
                P.mm(pb[j][:, u * 128:(u + 1) * 128], AR[:, u, 0, :], BT[:, u, :], True, True, [ARn, BTn], [f"pb{j}"])
            cur, curn, nx, nxn = iv["MWa"], IR("MWa"), iv["MWb"], IR("MWb")
            P.tt("vector", cur[:, :, 0, :], u128(pb[j][:]), mCb, ALU.mult, [f"pb{j}", "masks"], [curn])
            yield
            j = nxt("pb")
            for u in range(4):
                P.mm(pb[j][:, u * 128:(u + 1) * 128], iv["MQ"][:, u, 0:128], cur[:, u, 0, :], True, True, [IR("MQ"), curn], [f"pb{j}"])
            P.cp("scalar", nx[:, :, 0, :], u128(pb[j][:]), [f"pb{j}"], [nxn])
            P.tt("gpsimd", nx[:, :, 1, :], cur[:, :, 0, :], idb, ALU.add, [curn, "identb"], [nxn])
            j = nxt("pb")
            for u in range(4):
                P.mm(pb[j][:, u * 128:(u + 1) * 128], cur[:, u, 0, :], iv["MQ"][:, u, 0:128], True, True, [IR("MQ"), curn], [f"pb{j}"])
            curT, curTn, nxT, nxTn = iv["MTa"], IR("MTa"), iv["MTb"], IR("MTb")
            P.cp("scalar", curT[:], u128(pb[j][:]), [f"pb{j}"], [curTn])
            cur, curn, nx, nxn = nx, nxn, cur, curn
            yield
            for lev in range(1, 5):
                def mm_lev(u, o0, n0, o1, n1, cur=cur, curn=curn, curT=curT, curTn=curTn):
                    P.mm(o0, curT[:, u, :], cur[:, u, 0, :], True, True, [curTn, curn], [n0])
                    P.mm(o1, curT[:, u, :], cur[:, u, 1, :], True, True, [curTn, curn], [n1])
                j0, j1 = two_bank(mm_lev)
                P.cp("scalar", nx[:, :, 0, :], u128(pb[j0][:]), [f"pb{j0}"], [nxn])
                P.tt("vector", nx[:, :, 1, :], u128(pb[j1][:]), cur[:, :, 1, :], ALU.add, [f"pb{j1}", curn], [nxn])
                j = nxt("pb")
                for u in range(4):
                    P.mm(pb[j][:, u * 128:(u + 1) * 128], cur[:, u, 0, :], curT[:, u, :], True, True, [curn, curTn], [f"pb{j}"])
                P.cp("scalar", nxT[:], u128(pb[j][:]), [f"pb{j}"], [nxTn])
                cur, curn, nx, nxn = nx, nxn, cur, curn
                curT, curTn, nxT, nxTn = nxT, nxTn, curT, curTn
                yield
            j = nxt("pb")
            for u in range(4):
                P.mm(pb[j][:, u * 128:(u + 1) * 128], curT[:, u, :], cur[:, u, 1, :], True, True, [curTn, curn], [f"pb{j}"])
            P.tt("vector", nx[:, :, 1, :], u128(pb[j][:]), cur[:, :, 1, :], ALU.add, [f"pb{j}", curn], [nxn])
            W6, W6n = nx, nxn
            j = nxt("pb")
            for u in range(4):
                P.mm(pb[j][:, u * 64:(u + 1) * 64], fn["NP"][:, u, 0:128], Vb[:, u, :], True, True, [FR("NP"), Vbn], [f"pb{j}"])
            P.cp("scalar", fn["NVb"][:].rearrange("p u x -> p (u x)"), pb[j][:, 0:256], [f"pb{j}"], [FR("NVb")])
            yield

            def mm_d(u, o0, n0, o1, n1):
                P.mm(o0, W6[:, u, 1, :], iv["MQ"][:, u, 128:256], True, True, [W6n, IR("MQ")], [n0])
                P.mm(o1, W6[:, u, 1, :], iv["Btok"][:, u, :], True, True, [W6n, IR("Btok")], [n1])
            j0, j1 = two_bank(mm_d)
            P.cp("scalar", fn["XW"][:, :, 0:128], u128(pb[j0][:]), [f"pb{j0}"], [FR("XW")])
            P.cp("vector", fn["XW"][:, :, 128:256], u128(pb[j1][:]), [f"pb{j1}"], [FR("XW")])
            yield

            def mm_f(u, o0, n0, o1, n1):
                P.mm(o0, iv["Atok"][:, u, :], fn["XW"][:, u, 0:128], True, True, [IR("Atok"), FR("XW")], [n0])
                P.mm(o1, iv["Atok"][:, u, :], fn["XW"][:, u, 128:256], True, True, [IR("Atok"), FR("XW")], [n1])
            j0, j1 = two_bank(mm_f)
            P.tt("vector", fn["GY"][:], u128(pb[j0][:]), AR[:, :, 1, :], ALU.add, [f"pb{j0}", ARn], [FR("GY")])
            P.tt("vector", fn["GS"][:], u128(pb[j1][:]), idb, ALU.add, [f"pb{j1}", "identb"], [FR("GS")])
            yield

        def finish(ti, oc, q, z):
            isctx, idx = RW_ORDER1[ti]
            tg = 16 if isctx else idx
            sf, sb_ = fin[q]
            F0 = lambda nm: f"f{q}0_{nm}"
            F1 = lambda nm: f"f{q}1_{nm}"
            Vb, Vbn, gam = Vbq[z], f"Vb{z}", gamq[z]
            SFR = ("Sf", oc)
            for u in range(4):
                yo = pf[:, u * 64:(u + 1) * 64]
                P.mm(yo, sf["NP"][:, u, 128:256], Vb[:, u, :], True, False, [F0("NP"), Vbn], ["pf"])
                P.mm(yo, sf["XW"][:, u, 0:128], sf["NVb"][:, u, :], False, False, [F0("XW"), F0("NVb")], ["pf"])
                P.mm(yo, sb_["NP"][:, u, 128:256], Vb[:, u, :], False, False, [F1("NP"), Vbn], ["pf"])
                P.mm(yo, sb_["XW"][:, u, 0:128], sb_["NVb"][:, u, :], False, False, [F1("XW"), F1("NVb")], ["pf"])
                P.mm(yo, sf["GY"][:, u, :], Sf[:, oc, :], False, True, [F0("GY"), SFR], ["pf"])
                so = pf[:, 256:320]
                P.mm(so, sf["Ktok"][:, u, :], Vb[:, u, :], True, False, [F0("Ktok"), Vbn], ["pf"])
                P.mm(so, sf["XW"][:, u, 128:256], sf["NVb"][:, u, :], False, False, [F0("XW"), F0("NVb")], ["pf"])
                P.mm(so, sf["GS"][:, u, :], Sf[:, oc, :], False, True, [F0("GS"), SFR], ["pf"])
                P.ts("vector", Sf[:, oc, :], so, gam[0][:, u:u + 1], None, ALU.mult, None, ["pf", f"gam{z}0"], [SFR])
                yield
            P.cp("vector", YPs[:].rearrange("p u x -> p (u x)"), pf[:, 0:256], ["pf"], ["YPs"])
            P.dma("sync", S["yp"][tg, oc], YPs[:].rearrange("p u x -> p (u x)"), reads=["YPs"], writes=[("yp", tg, oc)], sem="YPs")
            j = nxt("pb")
            for u in range(4):
                so = pb[j][:, u * 64:(u + 1) * 64]
                P.mm(so, sb_["Ktok"][:, u, :], Vb[:, u, :], True, False, [F1("Ktok"), Vbn], [f"pb{j}"])
                P.mm(so, sb_["XW"][:, u, 128:256], sb_["NVb"][:, u, :], False, True, [F1("XW"), F1("NVb")], [f"pb{j}"])
            P.cp("scalar", SAs[:].rearrange("p u x -> p (u x)"), pb[j][:, 0:256], [f"pb{j}"], ["SAs"])
            P.dma("sync", S["sadd"][tg, oc], SAs[:].rearrange("p u x -> p (u x)"), reads=["SAs"], writes=[("sadd", tg, oc)], sem="SAs")
            P.dma("sync", S["gyb"][tg, oc], sb_["GY"][:].rearrange("p u x -> p (u x)"), reads=[F1("GY")], writes=[("gyb", tg, oc)], sem=F1("GY"))
            P.dma("sync", S["gsb"][tg, oc], sb_["GS"][:].rearrange("p u x -> p (u x)"), reads=[F1("GS")], writes=[("gsb", tg, oc)], sem=F1("GS"))
            yield

        NT = len(RW_ORDER1)
        NJ = NT * 8
        done = {"prep": set(), "c0": set(), "c1": set(), "fin": set(), "tprep": set()}

        def stream_P():
            for ti in range(NT):
                yield ("tprep", ti, lambda ti=ti: (ti == 0 or ("prep", (ti - 1) * 8 + 7) in donef), lambda ti=ti: tprep(ti))
                for oc in range(8):
                    k = ti * 8 + oc
                    yield ("prep", k, lambda k=k: ((k < 2 or (("c0", k - 2) in donef and ("c1", k - 2) in donef)) and (k < 3 or ("fin", k - 3) in donef)),
                           lambda ti=ti, oc=oc, k=k: prep(ti, oc, k % 2, k % 3))

        def stream_C(d):
            for k in range(NJ):
                ti, oc = divmod(k, 8)
                yield (f"c{d}", k, lambda k=k: (("prep", k) in donef and (k < 2 or ("fin", k - 2) in donef)),
                       lambda ti=ti, oc=oc, k=k: chain(ti, oc, k % 2, d, k % 3))

        def stream_F():
            for k in range(NJ):
                ti, oc = divmod(k, 8)
                yield ("fin", k, lambda k=k: (("c0", k) in donef and ("c1", k) in donef),
                       lambda ti=ti, oc=oc, k=k: finish(ti, oc, k % 2, k % 3))

        donef = set()
        load_hh(0)
        streams = [stream_C(0), stream_C(1), stream_F(), stream_P()]
        cur = [None] * 4
        pend = [None] * 4
        alive = [True] * 4
        while any(alive):
            progressed = False
            for si in range(4):
                if not alive[si]:
                    continue
                if cur[si] is None:
                    if pend[si] is None:
                        try:
                            pend[si] = next(streams[si])
                        except StopIteration:
                            alive[si] = False
                            continue
                    kind, k, ready, mk = pend[si]
                    if not ready():
                        continue
                    cur[si] = (kind, k, mk())
                    pend[si] = None
                kind, k, gen = cur[si]
                try:
                    next(gen)
                    progressed = True
                except StopIteration:
                    donef.add((kind, k))
                    cur[si] = None
                    progressed = True
            assert progressed or not any(alive), "scheduler stuck"


def stage_rwkv1a(P, io, G, hp, S):
    vec, masks, identb, identf, bones, onesb, rmask = (G[k] for k in ("vec", "masks", "identb", "identf", "bones", "onesb", "rmask"))
    with P.phase("rwkv1a"):
        wr = P.sb([128, 8, 1024], BF16)
        wk = P.sb([128, 8, 1024], BF16)
        wv = P.sb([128, 8, 1024], BF16)
        for w, nm in ((wr, "rwkv_wr"), (wk, "rwkv_wk"), (wv, "rwkv_wv")):
            P.dma("gpsimd", w[:], fm(io[nm]), writes=[nm], sem=nm)
        lw1 = P.sb([128, 8, 128], BF16)
        la1 = P.sb([128, 8, 128], BF16)
        g1 = P.sb([128, 8, 128], BF16)
        for d in range(2):
            P.dma("gpsimd", lw1[:, :, d * 64:(d + 1) * 64], io["rwkv_w1"][d].rearrange("(c p) j -> p c j", p=128), writes=["lw1"], sem=f"lw1{d}")
            P.dma("gpsimd", la1[:, :, d * 64:(d + 1) * 64], io["rwkv_a1"][d].rearrange("(c p) j -> p c j", p=128), writes=["la1"], sem=f"la1{d}")
        P.dma("gpsimd", g1[:], io["rwkv_g1"].rearrange("(c p) j -> p c j", p=128), writes=["g1"], sem="g1")
        w2s = P.sb([128, 1024], BF16)
        a2s = P.sb([128, 1024], BF16)
        g2 = P.sb([128, 1024], BF16)
        P.dma("gpsimd", w2s[:], io["rwkv_w2"].rearrange("d j f -> (d j) f"), writes=["w2s"], sem="w2s")
        P.dma("gpsimd", a2s[:], io["rwkv_a2"].rearrange("d j f -> (d j) f"), writes=["a2s"], sem="a2s")
        P.dma("gpsimd", g2[:], io["rwkv_g2"], writes=["g2"], sem="g2")

        hh = P.sb([128, 8, 384], F32)
        xx = P.sb([128, 8, 256], F32)
        xr = P.sb([128, 8, 256], BF16)
        xk = P.sb([128, 8, 256], BF16)
        xv = P.sb([128, 8, 256], BF16)
        xrot = P.sb([128, 8, 256], BF16)
        lwt = P.sb([128, 256], BF16)
        lat = P.sb([128, 256], BF16)
        sg = P.sb([128, 256], BF16)
        NSET = 3
        bufs = []
        for w_ in range(NSET):
            B_ = {"t": {}}
            for nm in ("r", "k", "sw0", "sw1", "ag0", "ag1", "kq", "lnv", "rs", "kkn", "fac", "kd0", "kd1", "b0", "b1",
                       "L", "Lx", "Lb", "E1", "E2", "E3", "ks"):
                B_["t"][nm] = P.sb([128, 256], F32, f"t{w_}_" + nm)
            B_["sqb"] = P.sb([128, 256], BF16)
            B_["RK"] = P.sb([128, 4, 2, 64], BF16)
            B_["VTbd"] = P.sb([128, 4, 128], F32)
            B_["GTbd"] = P.sb([128, 4, 128], F32)
            B_["Vf"] = P.sb([128, 4, 64], F32)
            B_["Gf"] = P.sb([128, 4, 64], F32)
            B_["ops"] = P.sb([128, 2, 4, 256], BF16)
            B_["vb"] = P.sb([128, 4, 64], BF16)
            B_["gam"] = P.sb([128, 2, 4], F32)
            bufs.append(B_)
            P.memset("gpsimd", B_["RK"][:], 0.0, [("RK", w_)])
            P.memset("gpsimd", B_["VTbd"][:], 0.0, [("VTbd", w_)])
            P.memset("gpsimd", B_["GTbd"][:], 0.0, [("GTbd", w_)])
        P.ns_set = frozenset(["r", "k", "sw0", "sw1", "ag0", "ag1", "kq", "lnv", "rs", "kkn", "fac", "kd0", "kd1", "b0", "b1",
                              "L", "Lx", "Lb", "E1", "E2", "E3", "ks", "sqb", "RK", "VTbd", "GTbd", "Vf", "Gf", "ops_st", "vb_st", "gam_st"])
        gamb_t = P.sb([128, 8, 4], F32)
        bon_t = P.sb([128, 8, 4], F32)
        ppt = [P.ps([128, 512], F32) for _ in range(4)]
        pp = [t_[:, 0:256] for t_ in ppt]
        pb = [P.ps([128, 512], F32) for _ in range(4)]
        cnt = {"pp": 0, "pb": 0}
        nmod = {"pp": 4, "pb": 4}

        def nxt(kind):
            i = cnt[kind] % nmod[kind]
            cnt[kind] += 1
            return i

        def v3(ap):
            return ap.rearrange("p (u s) -> p u s", s=64)

        def u128(ap):
            return ap.rearrange("p (u x) -> p u x", x=128)

        def load_hh(ti):
            isctx, idx = RW_ORDER1[ti]
            off = 4288 if isctx else 64 + 256 * idx
            P.dma("sync", hh[:], fm(hp[:, off - 64: off + 320]), writes=["hh"], sem="hh")

        def proj8(w_cols_fn, xb, bn, extra_r):
            i = nxt("pp")
            for c in range(8):
                P.mm(pp[i], w_cols_fn(c), xb[:, c, :], c == 0, c == 7, [(bn, c)] + extra_r, [f"pp{i}"])
            return i

        def tprep(ti):
            isctx, idx = RW_ORDER1[ti]
            hc = hh[:, :, 64:320]
            XXW = [("xx", c) for c in range(8)]
            if not isctx:
                h4 = hh[:, :, 64:320].rearrange("p c (r w) -> p c r w", w=64)
                x4 = xx[:].rearrange("p c (r w) -> p c r w", w=64)
                P.tt("vector", x4[:, 0:2, :, 1:64], h4[:, 0:2, :, 0:63], h4[:, 0:2, :, 1:64], ALU.subtract, ["hh"], XXW[0:2])
                P.ts("gpsimd", x4[:, 0:2, :, 0:1], h4[:, 0:2, :, 0:1], -1.0, 0.0, ALU.mult, ALU.add, ["hh"], [("xxe", 0)])
                P.tt("vector", x4[:, 2:4, :, 0:63], h4[:, 2:4, :, 1:64], h4[:, 2:4, :, 0:63], ALU.subtract, ["hh"], XXW[2:4])
                P.ts("gpsimd", x4[:, 2:4, :, 63:64], h4[:, 2:4, :, 63:64], -1.0, 0.0, ALU.mult, ALU.add, ["hh"], [("xxe", 1)])
                P.tt("gpsimd", xx[:, 4:6, :], hh[:, 4:6, 0:256], hh[:, 4:6, 64:320], ALU.subtract, ["hh"], XXW[4:6])
                P.tt("gpsimd", xx[:, 6:8, :], hh[:, 6:8, 128:384], hh[:, 6:8, 64:320], ALU.subtract, ["hh"], XXW[6:8])
            else:
                P.tt("vector", xx[:, 0:4, :], hh[:, 0:4, 63:319], hh[:, 0:4, 64:320], ALU.subtract, ["hh"], XXW[0:4] + [("xxe", 0)])
                P.tt("gpsimd", xx[:, 4:8, :], hh[:, 4:8, 65:321], hh[:, 4:8, 64:320], ALU.subtract, ["hh"], XXW[4:8] + [("xxe", 1)])
            yield

            def mk_xj(j, buf, bn):
                for c in range(8):
                    P.stt(buf[:, c, :], xx[:, c, :], vec[:, 5 + j, c:c + 1], hc[:, c, :], ALU.mult, ALU.add,
                          [("xx", c), ("xxe", 0), ("xxe", 1), "hh", "vec"], [(bn, c)])

            mk_xj(1, xrot, "xrot")
            yield
            i = proj8(lambda c: lw1[:, c, :], xrot, "xrot", ["lw1"])
            P.act(lwt[:], pp[i], AF.Tanh, [f"pp{i}"], ["lwt"])
            yield
            mk_xj(4, xrot, "xrot")
            yield
            i = proj8(lambda c: la1[:, c, :], xrot, "xrot", ["la1"])
            P.cp("scalar", lat[:], pp[i], [f"pp{i}"], ["lat"])
            yield
            mk_xj(5, xrot, "xrot")
            yield
            i = proj8(lambda c: g1[:, c, :], xrot, "xrot", ["g1"])
            P.act(sg[:], pp[i], AF.Sigmoid, [f"pp{i}"], ["sg"])
            yield
            mk_xj(0, xr, "xr")
            yield
            mk_xj(2, xk, "xk")
            yield
            mk_xj(3, xv, "xv")
            if ti + 1 < len(RW_ORDER1):
                load_hh(ti + 1)
            yield

        def prep(ti, oc, w):
            isctx, idx = RW_ORDER1[ti]
            tg = 16 if isctx else idx
            cs = slice(oc * 128, (oc + 1) * 128)
            B_ = bufs[w]
            t, sqb, RK, VTbd, GTbd, Vf, Gf = B_["t"], B_["sqb"], B_["RK"], B_["VTbd"], B_["GTbd"], B_["Vf"], B_["Gf"]
            ops_st, vb_st, gam_st = B_["ops"], B_["vb"], B_["gam"]
            i = proj8(lambda c: wr[:, c, cs], xr, "xr", ["rwkv_wr"])
            P.cp("scalar", t["r"][:], pp[i], [f"pp{i}"], ["r"])
            i = proj8(lambda c: wk[:, c, cs], xk, "xk", ["rwkv_wk"])
            P.cp("scalar", t["k"][:], pp[i], [f"pp{i}"], ["k"])
            yield
            i = proj8(lambda c: wv[:, c, cs], xv, "xv", ["rwkv_wv"])
            vt4 = VTbd[:].rearrange("p u (h s) -> p u h s", h=2)
            for h2 in range(2):
                sl = slice(h2 * 64, (h2 + 1) * 64)
                P.cp("scalar", vt4[sl, :, h2, :], v3(pp[i][sl, :]), [f"pp{i}"], ["VTbd"])
            i = nxt("pp")
            P.mm(pp[i], g2[:, cs], sg[:], True, True, ["g2", "sg"], [f"pp{i}"])
            gt4 = GTbd[:].rearrange("p u (h s) -> p u h s", h=2)
            for h2 in range(2):
                sl = slice(h2 * 64, (h2 + 1) * 64)
                P.cp("scalar", gt4[sl, :, h2, :], v3(pp[i][sl, :]), [f"pp{i}"], ["GTbd"])
            yield
            j = nxt("pb")
            for u in range(4):
                P.tr(pb[j][:, u * 128:(u + 1) * 128], VTbd[:, u, :], identf[:], ["VTbd", "identf"], [f"pb{j}"])
            pv = u128(pb[j][:])
            for h2 in range(2):
                sl = slice(h2 * 64, (h2 + 1) * 64)
                P.cp("scalar", Vf[sl, :, :], pv[sl, :, h2 * 64:(h2 + 1) * 64], [f"pb{j}"], ["Vf"])
            P.cp("gpsimd", vb_st[:], Vf[:], ["Vf"], ["vb_st"])
            P.dma("sync", S["vst"][tg, oc].rearrange("p (u s) -> p u s", s=64), Vf[:], reads=["Vf"], writes=[("vst", tg, oc)], sem="Vf")
            yield
            j = nxt("pb")
            for u in range(4):
                P.tr(pb[j][:, u * 128:(u + 1) * 128], GTbd[:, u, :], identf[:], ["GTbd", "identf"], [f"pb{j}"])
            pv = u128(pb[j][:])
            for h2 in range(2):
                sl = slice(h2 * 64, (h2 + 1) * 64)
                P.cp("scalar", Gf[sl, :, :], pv[sl, :, h2 * 64:(h2 + 1) * 64], [f"pb{j}"], ["Gf"])
            P.dma("sync", S["gst"][tg, oc].rearrange("p (u s) -> p u s", s=64), Gf[:], reads=["Gf"], writes=[("gst", tg, oc)], sem="Gf")
            yield
            for d in range(2):
                dl = slice(d * 64, (d + 1) * 64)
                i = nxt("pp")
                P.mm(pp[i], w2s[dl, cs], lwt[dl, :], True, True, ["w2s", "lwt"], [f"pp{i}"])
                P.act(t[f"sw{d}"][:], pp[i], AF.Sigmoid, [f"pp{i}", "vec"], [f"sw{d}"], bias=vec[:, 11 + d, oc:oc + 1])
                i = nxt("pp")
                P.mm(pp[i], a2s[dl, cs], lat[dl, :], True, True, ["a2s", "lat"], [f"pp{i}"])
                P.act(t[f"ag{d}"][:], pp[i], AF.Sigmoid, [f"pp{i}", "vec"], [f"ag{d}"], bias=vec[:, 13 + d, oc:oc + 1])
            yield
            P.ts("vector", t["kq"][:], t["k"][:], vec[:, 15, oc:oc + 1], None, ALU.mult, None, ["k", "vec"], ["kq"])
            P.act(sqb[:], t["kq"][:], AF.Square, ["kq"], ["sqb"])
            i = nxt("pp")
            P.mm(pp[i], bones[:], sqb[:], True, True, ["bones", "sqb"], [f"pp{i}"])
            P.act(t["lnv"][:], pp[i], AF.Ln, [f"pp{i}"], ["lnv"], bias=1e-12)
            P.act(t["rs"][:], t["lnv"][:], AF.Exp, ["lnv"], ["rs"], scale=-0.5)
            P.tt("vector", t["kkn"][:], t["kq"][:], t["rs"][:], ALU.mult, ["kq", "rs"], ["kkn"])
            yield
            for d in range(2):
                sw, ag, kd, bb = t[f"sw{d}"], t[f"ag{d}"], t[f"kd{d}"], t[f"b{d}"]
                EE = "vector"
                P.ts(EE, t["fac"][:], ag[:], vec[:, 16, oc:oc + 1], vec[:, 17, oc:oc + 1], ALU.mult, ALU.add, [f"ag{d}", "vec"], ["fac"])
                P.tt(EE, kd[:], t["k"][:], t["fac"][:], ALU.mult, ["k", "fac"], [f"kd{d}"])
                P.tt(EE, bb[:], t["kkn"][:], ag[:], ALU.mult, ["kkn", f"ag{d}"], [f"b{d}"])
                P.op("vector", lambda e, sw=sw: e.tensor_tensor_scan(out=t["L"][:], data0=rmask[:], data1=sw[:], initial=0.0,
                                                                      op0=ALU.mult, op1=ALU.add), [f"sw{d}", "rmask"], ["L"])
                L3 = v3(t["L"][:])
                if d == 0:
                    P.tt(EE, t["Lx"][:], t["L"][:], sw[:], ALU.subtract, ["L", f"sw{d}"], ["Lx"])
                    Li, Lin = t["L"], "L"
                else:
                    P.tt(EE, v3(t["Lx"][:]), L3[:, :, 63:64].broadcast_to([128, 4, 64]), L3, ALU.subtract, ["L"], ["Lx"])
                    P.tt(EE, t["Lb"][:], t["Lx"][:], sw[:], ALU.add, ["Lx", f"sw{d}"], ["Lb"])
                    Li, Lin = t["Lb"], "Lb"
                yield
                P.act(t["E1"][:], Li[:], AF.Exp, [Lin], ["E1"], scale=-C0)
                P.act(t["E3"][:], Li[:], AF.Exp, [Lin], ["E3"], scale=C0)
                P.act(t["E2"][:], t["Lx"][:], AF.Exp, ["Lx"], ["E2"], scale=-C0)
                yield
                P.stt(ops_st[:, d, 0, :], t["kkn"][:], -1.0, t["E2"][:], ALU.mult, ALU.mult, ["kkn", "E2"], ["ops_st"])
                P.tt("vector", ops_st[:, d, 1, :], t["r"][:], t["E1"][:], ALU.mult, ["r", "E1"], ["ops_st"])
                P.tt(EE, ops_st[:, d, 2, :], kd[:], t["E3"][:], ALU.mult, [f"kd{d}", "E3"], ["ops_st"])
                P.tt(EE, ops_st[:, d, 3, :], bb[:], t["E3"][:], ALU.mult, [f"b{d}", "E3"], ["ops_st"])
                E13 = v3(t["E1"][:])
                gsrc = E13[:, :, 63] if d == 0 else E13[:, :, 0]
                P.cp("vector", gam_st[:, d, :], gsrc, ["E1"], ["gam_st"])
                if d == 1:
                    P.cp("gpsimd", gamb_t[:, oc, :], gam_st[:, 1, :], ["gam_st"], ["gamb_t"])
                yield
            P.tt("vector", t["ks"][:], t["kd0"][:], t["kd1"][:], ALU.add, ["kd0", "kd1"], ["ks"])
            for h2 in range(2):
                sl = slice(h2 * 64, (h2 + 1) * 64)
                P.stt(RK[sl, :, h2, :], v3(t["r"][sl, :]), vec[sl, 18, oc:oc + 1], v3(t["ks"][sl, :]), ALU.mult, ALU.mult, ["r", "ks", "vec"], ["RK"])
            i = nxt("pp")
            for u in range(4):
                P.mm(pp[i][:, u:u + 1], RK[:, u, :, :].rearrange("p h s -> p (h s)"), onesb[:, 0:1], True, True, ["RK", "onesb"], [f"pp{i}"])
            P.cp("scalar", bon_t[:, oc, :], pp[i][:, 0:4], [f"pp{i}"], ["bon_t"])
            P.dma("sync", S["ops"][tg, oc], ops_st[:].rearrange("p d x n -> p (d x n)"), reads=["ops_st"], writes=[("ops", tg, oc)], sem="ops_st")
            P.dma("sync", S["vb"][tg, oc], vb_st[:].rearrange("p u s -> p (u s)"), reads=["vb_st"], writes=[("vb", tg, oc)], sem="vb_st")
            P.dma("sync", S["gam"][tg, oc], gam_st[:].rearrange("p d u -> p (d u)"), reads=["gam_st"], writes=[("gam", tg, oc)], sem="gam_st")
            yield


        NT = len(RW_ORDER1)
        load_hh(0)
        for ti in range(NT):
            isctx, idx = RW_ORDER1[ti]
            tg = 16 if isctx else idx
            for _ in tprep(ti):
                pass
            jobs = [(oc % NSET, prep(ti, oc, oc % NSET)) for oc in range(8)]
            active = []
            since = 99
            while jobs or active:
                if jobs and len(active) < NSET and (since >= 4 or not active):
                    active.append(jobs.pop(0))
                    since = 0
                since += 1
                for item in list(active):
                    P.ns = item[0]
                    try:
                        next(item[1])
                    except StopIteration:
                        active.remove(item)
                    P.ns = None
            P.dma("sync", S["gamb"][tg], gamb_t[:].rearrange("p a b -> p (a b)"), reads=["gamb_t"], writes=[("gamb", tg)], sem="gamb_t")
            P.dma("sync", S["bon"][tg], bon_t[:].rearrange("p a b -> p (a b)"), reads=["bon_t"], writes=[("bon", tg)], sem="bon_t")
        P.ns_set = frozenset()


def stage_rwkv1b(P, io, G, S):
    vec, masks, identb, identf, bones, onesb, rmask = (G[k] for k in ("vec", "masks", "identb", "identf", "bones", "onesb", "rmask"))
    with P.phase("rwkv1b"):
        YPs = P.sb([128, 4, 64], F32)
        SAs = P.sb([128, 4, 64], F32)
        Sf = P.sb([128, 8, 64], BF16)
        ARq = [[P.sb([128, 4, 2, 128], BF16, f"AR{q}{d}") for d in range(2)] for q in range(3)]
        KTq = [[P.sb([128, 4, 128], BF16, f"KT{q}{d}") for d in range(2)] for q in range(3)]
        BTq = [[P.sb([128, 4, 128], BF16, f"BT{q}{d}") for d in range(2)] for q in range(3)]
        stg = [P.sb([128, 2, 4, 256], BF16, f"stg{q}") for q in range(3)]
        Vbq = [P.sb([128, 4, 64], BF16, f"Vb{q}") for q in range(4)]
        gamq = [P.sb([128, 2, 4], F32, f"gam{q}") for q in range(4)]
        inv2 = []
        for q in range(2):
            row = []
            for d in range(2):
                st = {}
                for nm, shp in (("Atok", [128, 4, 128]), ("Btok", [128, 4, 128]), ("MQ", [128, 4, 256]), ("MWa", [128, 4, 2, 128]),
                                ("MWb", [128, 4, 2, 128]), ("MTa", [128, 4, 128]), ("MTb", [128, 4, 128])):
                    st[nm] = P.sb(shp, BF16, f"i{q}{d}_{nm}")
                row.append(st)
            inv2.append(row)
        fin = []
        for q in range(2):
            row = []
            for d in range(2):
                st = {}
                for nm, shp in (("Ktok", [128, 4, 128]), ("NP", [128, 4, 256]), ("XW", [128, 4, 256]), ("NVb", [128, 4, 64]),
                                ("GY", [128, 4, 128]), ("GS", [128, 4, 128])):
                    st[nm] = P.sb(shp, BF16, f"f{q}{d}_{nm}")
                row.append(st)
            fin.append(row)
        pf = P.ps([128, 512], F32)
        pb = [P.ps([128, 512], F32) for _ in range(7)]
        cnt = {"pb": 0}
        nmod = {"pb": 7}

        def nxt(kind):
            i = cnt[kind] % nmod[kind]
            cnt[kind] += 1
            return i

        for q in range(3):
            for d in range(2):
                P.memset("gpsimd", ARq[q][d][:], 0.0, [f"AR{q}{d}"])
                P.memset("gpsimd", KTq[q][d][:], 0.0, [f"KT{q}{d}"])
                P.memset("gpsimd", BTq[q][d][:], 0.0, [f"BT{q}{d}"])
        P.memset("gpsimd", Sf[:], 0.0, [("Sf", p) for p in range(8)])

        def v3(ap):
            return ap.rearrange("p (u s) -> p u s", s=64)

        def u128(ap):
            return ap.rearrange("p (u x) -> p u x", x=128)

        def loadjob(ti, oc, a, z):
            isctx, idx = RW_ORDER1[ti]
            tg = 16 if isctx else idx
            sg_ = stg[a]
            P.dma("sync", sg_[:].rearrange("p d x n -> p (d x n)"), S["ops"][tg, oc], writes=[f"stg{a}"], sem=f"stg{a}")
            P.dma("sync", Vbq[z][:].rearrange("p u s -> p (u s)"), S["vb"][tg, oc], writes=[f"Vb{z}"], sem=f"Vb{z}")
            P.dma("sync", gamq[z][:].rearrange("p d u -> p (d u)"), S["gam"][tg, oc], writes=[f"gam{z}"], sem=f"gam{z}")
            yield
            for d in range(2):
                ar5 = ARq[a][d][:].rearrange("p u a (h s) -> p u a h s", h=2)
                kt4 = KTq[a][d][:].rearrange("p u (h s) -> p u h s", h=2)
                bt4 = BTq[a][d][:].rearrange("p u (h s) -> p u h s", h=2)
                for h2 in range(2):
                    sl = slice(h2 * 64, (h2 + 1) * 64)
                    P.cp("gpsimd", ar5[sl, :, 0, h2, :], v3(sg_[sl, d, 0, :]), [f"stg{a}"], [f"AR{a}{d}"])
                    P.cp("gpsimd", ar5[sl, :, 1, h2, :], v3(sg_[sl, d, 1, :]), [f"stg{a}"], [f"AR{a}{d}"])
                    P.cp("gpsimd", kt4[sl, :, h2, :], v3(sg_[sl, d, 2, :]), [f"stg{a}"], [f"KT{a}{d}"])
                    P.cp("gpsimd", bt4[sl, :, h2, :], v3(sg_[sl, d, 3, :]), [f"stg{a}"], [f"BT{a}{d}"])
                    yield

        def chain(ti, oc, q, d, z, a):
            AR, KT, BT, Vb = ARq[a][d], KTq[a][d], BTq[a][d], Vbq[z]
            ARn, KTn, BTn, Vbn = f"AR{a}{d}", f"KT{a}{d}", f"BT{a}{d}", f"Vb{z}"
            iv, fn = inv2[q][d], fin[q][d]
            IR = lambda nm: f"i{q}{d}_{nm}"
            FR = lambda nm: f"f{q}{d}_{nm}"
            mS, mC = (0, 2) if d == 0 else (2, 0)
            mSI = masks[:, mS:mS + 2, :].rearrange("p a b -> p (a b)").unsqueeze(1).broadcast_to([128, 4, 256])
            mCb = masks[:, mC, :].unsqueeze(1).broadcast_to([128, 4, 128])
            idb = identb[:].unsqueeze(1).broadcast_to([128, 4, 128])
            for src, srcn, dst, dstn in ((AR[:, :, 0, :], ARn, iv["Atok"], IR("Atok")), (BT[:], BTn, iv["Btok"], IR("Btok")),
                                         (KT[:], KTn, fn["Ktok"], FR("Ktok"))):
                j = nxt("pb")
                pbt = pb[j][:].bitcast(BF16)
                for u in range(4):
                    P.tr(pbt[:, u * 128:(u + 1) * 128], src[:, u, :], identb[:], [srcn, "identb"], [f"pb{j}"])
                P.cp("scalar", dst[:].rearrange("p u x -> p (u x)"), pbt[:, 0:512], [f"pb{j}"], [dstn])
            mSb = masks[:, mS, :].unsqueeze(1).broadcast_to([128, 4, 128])
            mIb = masks[:, mS + 1, :].unsqueeze(1).broadcast_to([128, 4, 128])

            def two_bank(mm_fn):
                j0, j1 = nxt("pb"), nxt("pb")
                for u in range(4):
                    mm_fn(u, pb[j0][:, u * 128:(u + 1) * 128], f"pb{j0}", pb[j1][:, u * 128:(u + 1) * 128], f"pb{j1}")
                return j0, j1

            for lhs, lhsn, dst, dstn in ((BT, BTn, iv["MQ"], IR("MQ")), (KT, KTn, fn["NP"], FR("NP"))):
                def mm_ab(u, o0, n0, o1, n1, lhs=lhs, lhsn=lhsn):
                    P.mm(o0, lhs[:, u, :], AR[:, u, 0, :], True, True, [lhsn, ARn], [n0])
                    P.mm(o1, lhs[:, u, :], AR[:, u, 1, :], True, True, [lhsn, ARn], [n1])
                j0, j1 = two_bank(mm_ab)
                P.tt("vector", dst[:, :, 0:128], u128(pb[j0][:]), mSb, ALU.mult, [f"pb{j0}", "masks"], [dstn])
                P.tt("vector", dst[:, :, 128:256], u128(pb[j1][:]), mIb, ALU.mult, [f"pb{j1}", "masks"], [dstn])
            j = nxt("pb")
            for u in range(4):
                P.mm(pb[j][:, u * 128:(u + 1) * 128], AR[:, u, 0, :], BT[:, u, :], True, True, [ARn, BTn], [f"pb{j}"])
            cur, curn, nx, nxn = iv["MWa"], IR("MWa"), iv["MWb"], IR("MWb")
            P.tt("vector", cur[:, :, 0, :], u128(pb[j][:]), mCb, ALU.mult, [f"pb{j}", "masks"], [curn])
            yield
            j = nxt("pb")
            for u in range(4):
                P.mm(pb[j][:, u * 128:(u + 1) * 128], iv["MQ"][:, u, 0:128], cur[:, u, 0, :], True, True, [IR("MQ"), curn], [f"pb{j}"])
            P.cp("scalar", nx[:, :, 0, :], u128(pb[j][:]), [f"pb{j}"], [nxn])
            P.tt("gpsimd", nx[:, :, 1, :], cur[:, :, 0, :], idb, ALU.add, [curn, "identb"], [nxn])
            j = nxt("pb")
            for u in range(4):
                P.mm(pb[j][:, u * 128:(u + 1) * 128], cur[:, u, 0, :], iv["MQ"][:, u, 0:128], True, True, [IR("MQ"), curn], [f"pb{j}"])
            curT, curTn, nxT, nxTn = iv["MTa"], IR("MTa"), iv["MTb"], IR("MTb")
            P.cp("scalar", curT[:], u128(pb[j][:]), [f"pb{j}"], [curTn])
            cur, curn, nx, nxn = nx, nxn, cur, curn
            yield
            for lev in range(1, 5):
                def mm_lev(u, o0, n0, o1, n1, cur=cur, curn=curn, curT=curT, curTn=curTn):
                    P.mm(o0, curT[:, u, :], cur[:, u, 0, :], True, True, [curTn, curn], [n0])
                    P.mm(o1, curT[:, u, :], cur[:, u, 1, :], True, True, [curTn, curn], [n1])
                j0, j1 = two_bank(mm_lev)
                P.cp("scalar", nx[:, :, 0, :], u128(pb[j0][:]), [f"pb{j0}"], [nxn])
                P.tt("vector", nx[:, :, 1, :], u128(pb[j1][:]), cur[:, :, 1, :], ALU.add, [f"pb{j1}", curn], [nxn])
                j = nxt("pb")
                for u in range(4):
                    P.mm(pb[j][:, u * 128:(u + 1) * 128], cur[:, u, 0, :], curT[:, u, :], True, True, [curn, curTn], [f"pb{j}"])
                P.cp("scalar", nxT[:], u128(pb[j][:]), [f"pb{j}"], [nxTn])
                cur, curn, nx, nxn = nx, nxn, cur, curn
                curT, curTn, nxT, nxTn = nxT, nxTn, curT, curTn
                yield
            j = nxt("pb")
            for u in range(4):
                P.mm(pb[j][:, u * 128:(u + 1) * 128], curT[:, u, :], cur[:, u, 1, :], True, True, [curTn, curn], [f"pb{j}"])
            P.tt("vector", nx[:, :, 1, :], u128(pb[j][:]), cur[:, :, 1, :], ALU.add, [f"pb{j}", curn], [nxn])
            W6, W6n = nx, nxn
            j = nxt("pb")
            for u in range(4):
                P.mm(pb[j][:, u * 64:(u + 1) * 64], fn["NP"][:, u, 0:128], Vb[:, u, :], True, True, [FR("NP"), Vbn], [f"pb{j}"])
            P.cp("scalar", fn["NVb"][:].rearrange("p u x -> p (u x)"), pb[j][:, 0:256], [f"pb{j}"], [FR("NVb")])
            yield

            def mm_d(u, o0, n0, o1, n1):
                P.mm(o0, W6[:, u, 1, :], iv["MQ"][:, u, 128:256], True, True, [W6n, IR("MQ")], [n0])
                P.mm(o1, W6[:, u, 1, :], iv["Btok"][:, u, :], True, True, [W6n, IR("Btok")], [n1])
            j0, j1 = two_bank(mm_d)
            P.cp("scalar", fn["XW"][:, :, 0:128], u128(pb[j0][:]), [f"pb{j0}"], [FR("XW")])
            P.cp("vector", fn["XW"][:, :, 128:256], u128(pb[j1][:]), [f"pb{j1}"], [FR("XW")])
            yield

            def mm_f(u, o0, n0, o1, n1):
                P.mm(o0, iv["Atok"][:, u, :], fn["XW"][:, u, 0:128], True, True, [IR("Atok"), FR("XW")], [n0])
                P.mm(o1, iv["Atok"][:, u, :], fn["XW"][:, u, 128:256], True, True, [IR("Atok"), FR("XW")], [n1])
            j0, j1 = two_bank(mm_f)
            P.tt("vector", fn["GY"][:], u128(pb[j0][:]), AR[:, :, 1, :], ALU.add, [f"pb{j0}", ARn], [FR("GY")])
            P.tt("vector", fn["GS"][:], u128(pb[j1][:]), idb, ALU.add, [f"pb{j1}", "identb"], [FR("GS")])
            yield

        def finish(ti, oc, q, z):
            isctx, idx = RW_ORDER1[ti]
            tg = 16 if isctx else idx
            sf, sb_ = fin[q]
            F0 = lambda nm: f"f{q}0_{nm}"
            F1 = lambda nm: f"f{q}1_{nm}"
            Vb, Vbn, gamz = Vbq[z], f"Vb{z}", gamq[z]
            SFR = ("Sf", oc)
            for u in range(4):
                yo = pf[:, u * 64:(u + 1) * 64]
                P.mm(yo, sf["NP"][:, u, 128:256], Vb[:, u, :], True, False, [F0("NP"), Vbn], ["pf"])
                P.mm(yo, sf["XW"][:, u, 0:128], sf["NVb"][:, u, :], False, False, [F0("XW"), F0("NVb")], ["pf"])
                P.mm(yo, sb_["NP"][:, u, 128:256], Vb[:, u, :], False, False, [F1("NP"), Vbn], ["pf"])
                P.mm(yo, sb_["XW"][:, u, 0:128], sb_["NVb"][:, u, :], False, False, [F1("XW"), F1("NVb")], ["pf"])
                P.mm(yo, sf["GY"][:, u, :], Sf[:, oc, :], False, True, [F0("GY"), SFR], ["pf"])
                so = pf[:, 256:320]
                P.mm(so, sf["Ktok"][:, u, :], Vb[:, u, :], True, False, [F0("Ktok"), Vbn], ["pf"])
                P.mm(so, sf["XW"][:, u, 128:256], sf["NVb"][:, u, :], False, False, [F0("XW"), F0("NVb")], ["pf"])
                P.mm(so, sf["GS"][:, u, :], Sf[:, oc, :], False, True, [F0("GS"), SFR], ["pf"])
                P.ts("vector", Sf[:, oc, :], so, gamz[:, 0, u:u + 1], None, ALU.mult, None, ["pf", f"gam{z}"], [SFR])
                yield
            P.cp("vector", YPs[:].rearrange("p u x -> p (u x)"), pf[:, 0:256], ["pf"], ["YPs"])
            P.dma("sync", S["yp"][tg, oc], YPs[:].rearrange("p u x -> p (u x)"), reads=["YPs"], writes=[("yp", tg, oc)], sem="YPs")
            j = nxt("pb")
            for u in range(4):
                so = pb[j][:, u * 64:(u + 1) * 64]
                P.mm(so, sb_["Ktok"][:, u, :], Vb[:, u, :], True, False, [F1("Ktok"), Vbn], [f"pb{j}"])
                P.mm(so, sb_["XW"][:, u, 128:256], sb_["NVb"][:, u, :], False, True, [F1("XW"), F1("NVb")], [f"pb{j}"])
            P.cp("scalar", SAs[:].rearrange("p u x -> p (u x)"), pb[j][:, 0:256], [f"pb{j}"], ["SAs"])
            P.dma("sync", S["sadd"][tg, oc], SAs[:].rearrange("p u x -> p (u x)"), reads=["SAs"], writes=[("sadd", tg, oc)], sem="SAs")
            P.dma("sync", S["gyb"][tg, oc], sb_["GY"][:].rearrange("p u x -> p (u x)"), reads=[F1("GY")], writes=[("gyb", tg, oc)], sem=F1("GY"))
            P.dma("sync", S["gsb"][tg, oc], sb_["GS"][:].rearrange("p u x -> p (u x)"), reads=[F1("GS")], writes=[("gsb", tg, oc)], sem=F1("GS"))
            yield


        NT = len(RW_ORDER1)
        NJ = NT * 8
        donef = set()

        def stream_L():
            for k in range(NJ):
                ti, oc = divmod(k, 8)
                yield ("load", k, lambda k=k: ((k < 3 or (("c0", k - 3) in donef and ("c1", k - 3) in donef)) and (k < 4 or ("fin", k - 4) in donef)),
                       lambda ti=ti, oc=oc, k=k: loadjob(ti, oc, k % 3, k % 4))

        def stream_C(d, par):
            for k in range(par, NJ, 2):
                ti, oc = divmod(k, 8)
                yield (f"c{d}", k, lambda k=k: (("load", k) in donef and (k < 2 or ("fin", k - 2) in donef)),
                       lambda ti=ti, oc=oc, k=k: chain(ti, oc, k % 2, d, k % 4, k % 3))

        def stream_F():
            for k in range(NJ):
                ti, oc = divmod(k, 8)
                yield ("fin", k, lambda k=k: (("c0", k) in donef and ("c1", k) in donef),
                       lambda ti=ti, oc=oc, k=k: finish(ti, oc, k % 2, k % 4))

        streams = [stream_L(), stream_C(0, 0), stream_C(1, 0), stream_C(0, 1), stream_C(1, 1), stream_F()]
        NS_ = len(streams)
        cur = [None] * NS_
        pend = [None] * NS_
        alive = [True] * NS_
        while any(alive):
            progressed = False
            for si in range(NS_):
                if not alive[si]:
                    continue
                if cur[si] is None:
                    if pend[si] is None:
                        try:
                            pend[si] = next(streams[si])
                        except StopIteration:
                            alive[si] = False
                            continue
                    kind, k, ready, mk = pend[si]
                    if not ready():
                        continue
                    cur[si] = (kind, k, mk())
                    pend[si] = None
                kind, k, gen = cur[si]
                try:
                    next(gen)
                    progressed = True
                except StopIteration:
                    donef.add((kind, k))
                    cur[si] = None
                    progressed = True
            assert progressed or not any(alive), "scheduler stuck"


def stage_rwkv2(P, io, G, S, src, xa):
    vec, identb = G["vec"], G["identb"]
    GN_EPS = 64e-5
    with P.phase("rwkv2"):
        wo = P.sb([64, 16, 1024], BF16)
        P.dma("gpsimd", wo[:], io["rwkv_wo"].rearrange("(h v) f -> v h f", v=64), writes=["wo"], sem="wo")
        lnw = P.sb([128, 8, 64], F32)
        lnb = P.sb([128, 8, 64], F32)
        P.dma("sync", lnw[:], io["lnw_st"], writes=["lnw"], sem="lnw")
        P.dma("sync", lnb[:], io["lnb_st"], writes=["lnb"], sem="lnb")
        big = {}
        for nm in ("yp", "sadd", "vst", "gst"):
            big[nm] = [P.sb([128, 8, 256], F32, f"l_{nm}{b}") for b in range(2)]
        for nm in ("gyb", "gsb"):
            big[nm] = [P.sb([128, 8, 512], BF16, f"l_{nm}{b}") for b in range(2)]
        gamb = [P.sb([128, 8, 4], F32) for _ in range(2)]
        bon = [P.sb([128, 8, 4], F32) for _ in range(2)]
        xt = [P.sb([128, 8, 256], F32) for _ in range(2)]
        Sb = P.sb([128, 8, 64], BF16)
        ysb2 = [P.sb([128, 8, 64], F32) for _ in range(2)]
        ysq2 = [P.sb([128, 8, 64], F32) for _ in range(2)]
        tmpS = P.sb([128, 8, 64], F32)
        yn2 = [P.sb([128, 8, 64], F32) for _ in range(2)]
        bv2 = [P.sb([128, 8, 64], F32) for _ in range(2)]
        ob2 = [P.sb([128, 8, 64], BF16) for _ in range(2)]
        st2 = [{nm: P.sb([128, 8], F32, f"g{k_}_" + nm) for nm in ("s1", "s2", "mean", "msq", "var", "lnv", "rstd")} for k_ in range(2)]
        OT = P.sb([64, 16, 256], BF16)
        py = [P.ps([128, 512], F32) for _ in range(2)]
        pS = P.ps([128, 512], F32)
        ptr = P.ps([128, 1024], F32)
        pw = [P.ps([128, 512], F32) for _ in range(2)]
        P.memset("gpsimd", Sb[:], 0.0, ["Sb"])

        def load(k):
            isctx, idx = RW_ORDER2[k]
            tg = 16 if isctx else idx
            b = k % 2
            for nm in ("yp", "sadd", "vst", "gst", "gyb", "gsb"):
                P.dma("sync", big[nm][b][:], S[nm][tg].rearrange("o p x -> p o x"), writes=[f"{nm}{b}"], sem=f"{nm}{b}")
            P.dma("sync", gamb[b][:].rearrange("p a b -> p (a b)"), S["gamb"][tg], writes=[f"gamb{b}"], sem=f"gamb{b}")
            P.dma("sync", bon[b][:].rearrange("p a b -> p (a b)"), S["bon"][tg], writes=[f"bon{b}"], sem=f"bon{b}")
            c0 = T if isctx else idx * 256
            P.dma("sync", xt[b][:], fm(src[:, c0:c0 + 256]), writes=[f"xt{b}"], sem=f"xt{b}")

        load(0)
        for k, (isctx, idx) in enumerate(RW_ORDER2):
            b = k % 2
            if k + 1 < len(RW_ORDER2):
                load(k + 1)
            c0 = T if isctx else idx * 256
            _, _, gates = mod_scalars(G, 0, 0, isctx)
            bc = lambda ap: ap.unsqueeze(2).broadcast_to([128, 8, 64])
            def chain_part(u):
                us = slice(u * 64, (u + 1) * 64)
                q_ = u % 2
                for oc in range(8):
                    P.mm(py[q_][:, oc * 64:(oc + 1) * 64], big["gyb"][b][:, oc, u * 128:(u + 1) * 128], Sb[:, oc, :], True, True, [f"gyb{b}", "Sb"], [f"py{q_}"])
                for oc in range(8):
                    P.mm(pS[:, oc * 64:(oc + 1) * 64], big["gsb"][b][:, oc, u * 128:(u + 1) * 128], Sb[:, oc, :], True, True, [f"gsb{b}", "Sb"], ["pS"])
                pS3 = pS[:].rearrange("p (o v) -> p o v", v=64)
                P.tt("vector", tmpS[:], pS3, big["sadd"][b][:, :, us], ALU.add, ["pS", f"sadd{b}"], ["tmpS"])
                P.tt("vector", Sb[:], tmpS[:], bc(gamb[b][:, :, u]), ALU.mult, ["tmpS", f"gamb{b}"], ["Sb"])

            def read_part(u):
                us = slice(u * 64, (u + 1) * 64)
                q_ = u % 2
                ysb, ysq, yn, bv, ob, st = ysb2[q_], ysq2[q_], yn2[q_], bv2[q_], ob2[q_], st2[q_]
                N = lambda nm: f"{nm}{q_}"
                py3 = py[q_][:].rearrange("p (o v) -> p o v", v=64)
                P.tt("vector", ysb[:], py3, big["yp"][b][:, :, us], ALU.add, [f"py{q_}", f"yp{b}"], [N("ysb")])
                P.tt("gpsimd", bv[:], big["vst"][b][:, :, us], bc(bon[b][:, :, u]), ALU.mult, [f"vst{b}", f"bon{b}"], [N("bv")])
                yield
                P.op("vector", lambda e: e.tensor_reduce(out=st["s1"][:], in_=ysb[:], axis=AX.X, op=ALU.add), [N("ysb")], [N("s1")])
                P.tt("gpsimd", ysq[:], ysb[:], ysb[:], ALU.mult, [N("ysb")], [N("ysq")])
                yield
                P.op("vector", lambda e: e.tensor_reduce(out=st["s2"][:], in_=ysq[:], axis=AX.X, op=ALU.add), [N("ysq")], [N("s2")])
                P.ts("vector", st["mean"][:], st["s1"][:], 1.0 / 64, None, ALU.mult, None, [N("s1")], [N("mean")])
                P.tt("vector", st["msq"][:], st["mean"][:], st["mean"][:], ALU.mult, [N("mean")], [N("msq")])
                P.stt(st["var"][:], st["s2"][:], 1.0 / 64, st["msq"][:], ALU.mult, ALU.subtract, [N("s2"), N("msq")], [N("var")])
                yield
                P.act(st["lnv"][:], st["var"][:], AF.Ln, [N("var")], [N("lnv")], bias=GN_EPS)
                P.act(st["rstd"][:], st["lnv"][:], AF.Exp, [N("lnv")], [N("rstd")], scale=-0.5)
                P.tt("gpsimd", yn[:], ysb[:], bc(st["mean"][:]), ALU.subtract, [N("ysb"), N("mean")], [N("yn")])
                yield
                P.tt("vector", yn[:], yn[:], bc(st["rstd"][:]), ALU.mult, [N("yn"), N("rstd")], [N("yn")])
                yield
                P.tt("gpsimd", yn[:], yn[:], lnw[:], ALU.mult, [N("yn"), "lnw"], [N("yn")])
                yield
                P.tt("vector", yn[:], yn[:], lnb[:], ALU.add, [N("yn"), "lnb"], [N("yn")])
                yield
                P.tt("gpsimd", yn[:], yn[:], bv[:], ALU.add, [N("yn"), N("bv")], [N("yn")])
                yield
                P.tt("vector", ob[:], yn[:], big["gst"][b][:, :, us], ALU.mult, [N("yn"), f"gst{b}"], [N("ob")])
                yield
                ptb = ptr[:].bitcast(BF16)
                for oc in range(8):
                    P.tr(ptb[0:64, oc * 128:(oc + 1) * 128], ob[:, oc, :], identb[:], [N("ob"), "identb"], ["ptr"])
                P.cp("scalar", OT[:, :, us], ptb[0:64, 0:1024].rearrange("p (h t) -> p h t", t=64), ["ptr"], ["OT"])
                yield

            def chain_all():
                for u in range(3, -1, -1):
                    chain_part(u)
                    yield

            jobs = [read_part(u) for u in range(3, -1, -1)]
            cgen = chain_all()
            next(cgen)
            active = []
            started = 0
            while jobs or active:
                while jobs and len(active) < 2:
                    if started >= 1:
                        try:
                            next(cgen)
                        except StopIteration:
                            pass
                    active.append(jobs.pop(0))
                    started += 1
                for gen in list(active):
                    try:
                        next(gen)
                    except StopIteration:
                        active.remove(gen)
            for oc in range(8):
                j = oc % 2
                for h in range(16):
                    P.mm(pw[j][:, 0:256], wo[:, h, oc * 128:(oc + 1) * 128], OT[:, h, :], h == 0, h == 15, ["wo", "OT"], [f"pw{j}"])
                P.stt(xt[b][:, oc, :], pw[j][:, 0:256], gates[oc], xt[b][:, oc, :], ALU.mult, ALU.add, [f"pw{j}", f"xt{b}", "modv"], [f"xt{b}"])
            P.dma("sync", fm(xa[:, c0:c0 + 256]), xt[b][:], reads=[f"xt{b}"], writes=[("xa", k)], sem=f"xt{b}")


def stage_qkv(P, io, G, hb, qtd, Kz, VA):
    vec, bones, perm = G["vec"], G["bones"], G["perm"]
    with P.phase("qkv"):
        wq = P.sb([128, 8, 1024], BF16)
        wkd = P.sb([128, 8, 512], BF16)
        wv = P.sb([128, 8, 256], BF16)
        P.dma("gpsimd", wq[:], fm(io["attn_wq"]), writes=["wq"], sem="wq")
        P.dma("gpsimd", wkd[:], fm(io["attn_wkd"]), writes=["wkd"], sem="wkd")
        P.dma("gpsimd", wv[:], fm(io["attn_wv"]), writes=["wv"], sem="wv")
        ht = [P.sb([128, 8, 512], BF16) for _ in range(2)]
        cs = [P.sb([128, 512], F32) for _ in range(2)]
        sn = [P.sb([128, 512], F32) for _ in range(2)]
        NB = 2
        qf = [P.sb([128, 512], F32) for _ in range(NB)]
        sqb = [P.sb([128, 512], BF16) for _ in range(NB)]
        lnv = [P.sb([128, 512], F32) for _ in range(NB)]
        rstd = [P.sb([128, 512], F32) for _ in range(NB)]
        qh = [P.sb([128, 512], F32) for _ in range(NB)]
        qhb = [P.sb([128, 512], BF16) for _ in range(NB)]
        t1 = [P.sb([128, 512], F32) for _ in range(NB)]
        t2 = [P.sb([128, 512], F32) for _ in range(NB)]
        qst = [P.sb([128, 8, 512], BF16) for _ in range(2)]
        pp = [P.ps([128, 512], F32) for _ in range(6)]
        cnt = [0, 0]

        def nxt():
            cnt[0] += 1
            return cnt[0] % 6

        P.memset("gpsimd", VA[:], 0.0, ["VA0"])
        P.memset("gpsimd", VA[:].rearrange("p k (j x) -> p k j x", x=65)[:, :, 0:5, 64:65], 1.0, ["VA0"])
        P.memset("gpsimd", Kz[0][64:128, :, :], 0.0, ["Kz0z"])
        P.memset("gpsimd", Kz[1][0:64, :, :], 0.0, ["Kz1z"])
        tiles = ALL_TILES

        def load(i):
            c0, tw, isctx = tiles[i]
            b = i % 2
            P.dma("sync", ht[b][:, :, :tw], fm(hb[:, c0:c0 + tw]), writes=[f"ht{b}"], sem=f"ht{b}")
            if not isctx:
                P.dma("sync", cs[b][:, :tw], io["cosT"][:, c0:c0 + tw], writes=[f"cs{b}"], sem=f"cs{b}")
                P.dma("sync", sn[b][:, :tw], io["sinT"][:, c0:c0 + tw], writes=[f"sn{b}"], sem=f"sn{b}")

        def normrope(wcols, nscal, dsts, b, tw, isctx, wname, dres="dstqk"):
            cnt[1] += 1
            n = cnt[1] % NB
            i = nxt()
            for c in range(8):
                P.mm(pp[i][:, :tw], wcols(c), ht[b][:, c, :tw], c == 0, c == 7, [wname, f"ht{b}"], [f"pp{i}"])
            P.cp("scalar", qf[n][:, :tw], pp[i][:, :tw], [f"pp{i}"], [f"qf{n}"])
            P.act(sqb[n][:, :tw], qf[n][:, :tw], AF.Square, [f"qf{n}"], [f"sqb{n}"])
            yield
            i = nxt()
            P.mm(pp[i][:, :tw], bones[:], sqb[n][:, :tw], True, True, ["bones", f"sqb{n}"], [f"pp{i}"])
            P.act(lnv[n][:, :tw], pp[i][:, :tw], AF.Ln, [f"pp{i}"], [f"lnv{n}"], bias=1e-6, scale=1.0 / 64)
            P.act(rstd[n][:, :tw], lnv[n][:, :tw], AF.Exp, [f"lnv{n}"], [f"rstd{n}"], scale=-0.5)
            yield
            P.stt(qh[n][:, :tw], qf[n][:, :tw], nscal, rstd[n][:, :tw], ALU.mult, ALU.mult, [f"qf{n}", f"rstd{n}", "vec"], [f"qh{n}"])
            if isctx:
                for dst, sl in dsts:
                    P.cp("vector", dst, qh[n][sl, :tw], [f"qh{n}"], [dres])
                return
            P.cp("vector", qhb[n][:, :tw], qh[n][:, :tw], [f"qh{n}"], [f"qhb{n}"])
            yield
            i = nxt()
            P.mm(pp[i][:, :tw], perm[:], qhb[n][:, :tw], True, True, ["perm", f"qhb{n}"], [f"pp{i}"])
            P.tt("vector", t1[n][:, :tw], qh[n][:, :tw], cs[b][:, :tw], ALU.mult, [f"qh{n}", f"cs{b}"], [f"t1{n}"])
            P.tt("vector", t2[n][:, :tw], pp[i][:, :tw], sn[b][:, :tw], ALU.mult, [f"pp{i}", f"sn{b}"], [f"t2{n}"])
            yield
            for dst, sl in dsts:
                P.tt("vector", dst, t1[n][sl, :tw], t2[n][sl, :tw], ALU.add, [f"t1{n}", f"t2{n}"], [dres])

        ALLP = slice(0, 128)
        load(0)
        for i, (c0, tw, isctx) in enumerate(tiles):
            b = i % 2
            if i + 1 < len(tiles):
                load(i + 1)
            jobs = []
            if not isctx:
                for oc in range(8):
                    jobs.append(normrope(lambda c, oc=oc: wq[:, c, oc * 128:(oc + 1) * 128], vec[:, 19, oc:oc + 1], [(qst[b][:, oc, :tw], ALLP)], b, tw, False, "wq",
                                         dres=(f"qst{b}", oc)))
            for g in range(4):
                jobs.append(normrope(lambda c, g=g: wkd[:, c, g * 128:(g + 1) * 128], vec[:, 20, 0:1],
                                     [(Kz[0][0:64, g, c0:c0 + tw], slice(0, 64)), (Kz[1][64:128, g, c0:c0 + tw], slice(64, 128))], b, tw, isctx, "wkd"))

            def vjob():
                for sub in range(tw // 128):
                    kt = c0 // 128 + sub
                    j = nxt()
                    for c in range(8):
                        P.mm(pp[j][:, 0:256], ht[b][:, c, sub * 128:(sub + 1) * 128], wv[:, c, :], c == 0, c == 7, ["wv", f"ht{b}"], [f"pp{j}"])
                    P.cp("scalar", VA[:, kt, 65:325].rearrange("p (g x) -> p g x", x=65)[:, :, 0:64],
                         pp[j][:, 0:256].rearrange("p (g d) -> p g d", d=64), [f"pp{j}", "VA0"], [("VA", kt)])
                    yield

            jobs.append(vjob())
            active = []
            while jobs or active:
                while jobs and len(active) < 2:
                    active.append(jobs.pop(0))
                for gen in list(active):
                    try:
                        next(gen)
                    except StopIteration:
                        active.remove(gen)
            if not isctx:
                P.dma("sync", fm(qtd[:, c0:c0 + tw]), qst[b][:, :, :tw], reads=[(f"qst{b}", oc) for oc in range(8)], writes=[("qtd", i)], sem=f"qst{b}")


def stage_attn(P, io, G, qtd, Kz, VA, xa):
    with P.phase("attn"):
        wo = P.sb([128, 8, 1024], BF16)
        P.dma("gpsimd", wo[:], fm(io["attn_wo"]), writes=["wo"], sem="wo")
        sel = P.sb([128, 2, 128], F32)
        P.dma("sync", sel[:], io["c_sel"], writes=["sel"], sem="sel")
        PT = [P.sb([128, 1024], BF16) for _ in range(3)]
        osb = [P.sb([128, 512], F32) for _ in range(2)]
        rb = [P.sb([128, 512], F32) for _ in range(2)]
        xt = P.sb([128, 8, 512], F32)
        QB = [P.sb([128, 8, 512], BF16) for _ in range(2)]
        psS = [P.ps([128, 1024], F32) for _ in range(2)]
        psO = [P.ps([128, 512], F32) for _ in range(2)]
        psB = P.ps([128, 512], F32)
        pX = [P.ps([128, 512], F32) for _ in range(1)]
        _, _, gates = mod_scalars(G, 1, 0, False)
        for k in range(2):
            P.memset("gpsimd", osb[k][:], 0.0, [f"osb{k}"])
        def loadq(qb):
            P.dma("sync", QB[qb % 2][:], fm(qtd[:, qb * 512:(qb + 1) * 512]), writes=[("QT", h, qb) for h in range(16)], sem=f"QB{qb % 2}")

        loadq(0)
        for qb in range(8):
            qsl = slice(qb * 512, (qb + 1) * 512)
            QT = QB[qb % 2]
            if qb + 1 < 8:
                loadq(qb + 1)
            P.dma("sync", xt[:], fm(xa[:, qsl]), writes=["xt"], sem="xt")
            steps = [(h, kp) for h in range(16) for kp in range(17)]

            def S(i):
                h, kp = steps[i]
                g, oc, h2 = h // 4, h // 2, h % 2
                for e_ in range(2):
                    kt = 2 * kp + e_
                    P.mm(psS[i % 2][:, e_ * 512:(e_ + 1) * 512], Kz[h2][:, g, kt * 128:(kt + 1) * 128], QT[:, oc, :], True, True,
                         ["Kz", ("QT", h, qb)], [f"psS{i % 2}"])

            def epi_a(h):
                o = h % 2
                P.cp("vector", osb[o][:], psO[o][:], [f"psO{o}"], [f"osb{o}"])

            def epi_b(h):
                oc, h2, o = h // 2, h % 2, h % 2
                hs = slice(h2 * 64, h2 * 64 + 64)
                P.mm(psB[:, :], sel[:, h2, :], osb[o][:], True, True, ["sel", f"osb{o}"], ["psB"])
                P.op("vector", lambda e, o=o, hs=hs: e.reciprocal(out=rb[o][hs, :], in_=psB[hs, :]), ["psB"], [f"rb{o}"])
                P.tt("gpsimd", QT[hs, oc, :], osb[o][hs, :], rb[o][hs, :], ALU.mult, [f"osb{o}", f"rb{o}"], [("QT", h, qb)])

            S(0)
            pend = {}
            for i, (h, kp) in enumerate(steps):
                g, h2, o = h // 4, h % 2, h % 2
                if i + 1 < len(steps):
                    S(i + 1)
                p_ = i % 3
                P.act(PT[p_][:], psS[i % 2][:, :], AF.Exp, [f"psS{i % 2}"], [f"PT{p_}"], scale=0.125)
                v0 = 65 + 65 * g if h2 == 0 else 1 + 65 * g
                for e_ in range(2):
                    kt = 2 * kp + e_
                    P.mm(psO[o][:, :], VA[:, kt, v0:v0 + 128], PT[p_][:, e_ * 512:(e_ + 1) * 512], kt == 0, kt == 33, [f"PT{p_}", "VA"], [f"psO{o}"])
                if kp == 16:
                    epi_a(h)
                    pend[i + 3] = h
                if i in pend:
                    epi_b(pend.pop(i))
            for k in sorted(pend):
                epi_b(pend[k])
            for oc in range(8):
                j = 0
                for c in range(8):
                    P.mm(pX[j][:, :], wo[:, c, oc * 128:(oc + 1) * 128], QT[:, c, :], c == 0, c == 7,
                         ["wo", ("QT", 2 * c, qb), ("QT", 2 * c + 1, qb)], [f"pX{j}"])
                P.stt(xt[:, oc, :], pX[j][:, :], gates[oc], xt[:, oc, :], ALU.mult, ALU.add, [f"pX{j}", "xt", "modv"], ["xt"])
            P.dma("sync", fm(xa[:, qsl]), xt[:], reads=["xt"], writes=[("xa", qb)], sem="xt")


IN_SHAPES = {
    "xin": [D, TT], "cvec": [128, 8, 2], "w_mod": [2, D, 6 * D], "b_mod": [2, 6 * D], "vecs": [128, NV, 8],
    "mlp_w1": [2, D, 4 * D], "mlp_w2": [2, 4 * D, D],
    "rwkv_wr": [D, D], "rwkv_wk": [D, D], "rwkv_wv": [D, D], "rwkv_wo": [D, D],
    "rwkv_w1": [2, D, 64], "rwkv_w2": [2, 64, D], "rwkv_a1": [2, D, 64], "rwkv_a2": [2, 64, D],
    "rwkv_g1": [D, 128], "rwkv_g2": [128, D], "lnw_st": [128, 8, 64], "lnb_st": [128, 8, 64],
    "attn_wq": [D, D], "attn_wkd": [D, 512], "attn_wv": [D, 256], "attn_wo": [D, D],
    "cosT": [128, T], "sinT": [128, T],
    "c_ident": [128, 128], "c_ones": [128, 128], "c_bones": [128, 128], "c_masks": [128, 4, 128],
    "c_perm": [128, 128], "c_rmask": [128, 256], "c_sel": [128, 2, 128],
}


class IO(dict):
    def __init__(self, nc):
        super().__init__()
        self.nc = nc
        self.used = []

    def __missing__(self, k):
        ap = self.nc.dram_tensor(k, IN_SHAPES[k], F32, kind="ExternalInput").ap()
        self[k] = ap
        self.used.append(k)
        return ap

    def scratch(self, name, shape, dtype):
        return self.nc.dram_tensor(name, list(shape), dtype, kind="Internal").ap()

    def output(self, name, shape, dtype=F32):
        return self.nc.dram_tensor(name, list(shape), dtype, kind="ExternalOutput").ap()


def build(stages="all", dbg=None):
    nc = bass.Bass("TRN2", target_bir_lowering=False)
    io = IO(nc)
    P = Prog(nc)
    G = {}
    outs = {}
    stage_init(P, io, G)
    xa = io.scratch("xa", [D, TT], F32)
    hb = io.scratch("hb", [D, TT], BF16)
    if stages == "t_mlp":
        outs["dbg_h"] = io.output("dbg_h", [D, TT], BF16)
        stage_norm(P, io, G, "n_t", io["xin"], ALL_TILES,
                   lambda ic: mod_scalars(G, 0, 1, ic)[0], lambda ic: mod_scalars(G, 0, 1, ic)[1],
                   lambda c0, tw, ic: fm(hb[:, c0:c0 + tw]), BF16)
        with P.phase("copy"):
            P.dma("sync", xa, io["xin"], writes=["xa"], sem="cpa")
            P.dma("sync", outs["dbg_h"], hb, writes=["o"], sem="cpb")
        stage_mlp(P, io, G, 0, ALL_TILES, xa, hb)
        outs["y"] = io.output("y", [D, TT])
        fin = [G["vec"][:, 4, c:c + 1] for c in range(8)]
        stage_norm(P, io, G, "final", xa, ALL_TILES, lambda ic: fin, lambda ic: None,
                   lambda c0, tw, ic: fm(outs["y"][:, c0:c0 + tw]), F32)
    if stages in ("all", "l0", "l1pre"):
        hp = io.scratch("hp", [D, 4608], F32)
        S = rw_scratch(io)
        with P.phase("zpad"):
            z = P.sb([128, 8, 64], F32)
            P.memset("vector", z[:], 0.0, ["z"])
            for k, o in enumerate((0, 64 + T, 4224, 4288 + C)):
                P.dma("sync", fm(hp[:, o:o + 64]), z[:], reads=["z"], writes=[("hpz", k)], sem=f"z{k}")

        def hdst(c0, tw, ic):
            o = 4288 if ic else 64 + c0
            return fm(hp[:, o:o + tw])

        def hbdst(c0, tw, ic):
            return fm(hb[:, c0:c0 + tw])

        def ms(l, kind, which):
            return lambda ic: mod_scalars(G, l, kind, ic)[which]

        stage_norm(P, io, G, "n_mix0", io["xin"], ALL_TILES, ms(0, 0, 0), ms(0, 0, 1), hdst, F32)
        stage_rwkv1a(P, io, G, hp, S)
        stage_rwkv1b(P, io, G, S)
        stage_rwkv2(P, io, G, S, io["xin"], xa)
        stage_norm(P, io, G, "n_mlp0", xa, ALL_TILES, ms(0, 1, 0), ms(0, 1, 1), hbdst, BF16)
        stage_mlp(P, io, G, 0, ALL_TILES, xa, hb)
        if stages == "l0":
            outs["y"] = io.output("y", [D, TT])
            with P.phase("copyout"):
                P.dma("sync", outs["y"], xa, writes=["o"], sem="cpa")
        else:
            stage_norm(P, io, G, "n_mix1", xa, ALL_TILES, ms(1, 0, 0), ms(1, 0, 1), hbdst, BF16)
            with P.scope():
                QT = io.scratch("qtd", [D, T], BF16)
                Kz = [P.ssb([128, 4, TT], BF16, f"Kz{k}") for k in range(2)]
                VA = P.ssb([128, 34, 390], BF16, "VA")
                stage_qkv(P, io, G, hb, QT, Kz, VA)
                stage_attn(P, io, G, QT, Kz, VA, xa)
            if stages == "l1pre":
                outs["y"] = io.output("y", [D, TT])
                with P.phase("copyout"):
                    P.dma("sync", outs["y"], xa, writes=["o"], sem="cpa")
            else:
                stage_norm(P, io, G, "n_mlp1", xa, LAT_TILES, ms(1, 1, 0), ms(1, 1, 1), hbdst, BF16)
                stage_mlp(P, io, G, 1, LAT_TILES, xa, hb)
                outs["y"] = io.output("y", [D, T])
                fin = [G["vec"][:, 4, c:c + 1] for c in range(8)]
                stage_norm(P, io, G, "final", xa, LAT_TILES, lambda ic: fin, lambda ic: None,
                           lambda c0, tw, ic: fm(outs["y"][:, c0:c0 + tw]), F32)
    if stages == "t_rwkv":
        hp = io.scratch("hp", [D, 4608], F32)
        S = rw_scratch(io)
        with P.phase("zpad"):
            z = P.sb([128, 8, 64], F32)
            P.memset("vector", z[:], 0.0, ["z"])
            for k, o in enumerate((0, 64 + T, 4224, 4288 + C)):
                P.dma("sync", fm(hp[:, o:o + 64]), z[:], reads=["z"], writes=[("hpz", k)], sem=f"z{k}")
        def hdst(c0, tw, ic):
            o = 4288 if ic else 64 + c0
            return fm(hp[:, o:o + tw])
        stage_norm(P, io, G, "n_mix0", io["xin"], ALL_TILES,
                   lambda ic: mod_scalars(G, 0, 0, ic)[0], lambda ic: mod_scalars(G, 0, 0, ic)[1], hdst, F32)
        stage_rwkv1(P, io, G, hp, S)
        stage_rwkv2(P, io, G, S, io["xin"], xa)
        outs["y"] = io.output("y", [D, TT])
        with P.phase("copyout"):
            P.dma("sync", outs["y"], xa, writes=["o"], sem="cpa")
    P.close()
    return nc, io.used, list(outs.keys()), P


def fmv(v):
    return np.ascontiguousarray(np.asarray(v, np.float32).reshape(8, 128).T)


def host_consts():
    c = {}
    c["c_ident"] = np.eye(128, dtype=np.float32)
    c["c_ones"] = np.ones((128, 128), np.float32)
    blk = np.zeros((128, 128), np.float32)
    blk[:64, :64] = 1
    blk[64:, 64:] = 1
    c["c_bones"] = blk
    i = np.arange(64)
    us = (i[:, None] < i[None, :]).astype(np.float32)
    ui = (i[:, None] <= i[None, :]).astype(np.float32)
    m = np.zeros((128, 4, 128), np.float32)
    for k, mk in enumerate([us, ui, us.T, ui.T]):
        m[:64, k, :64] = mk
        m[64:, k, 64:] = mk
    c["c_masks"] = m
    Pm = np.zeros((128, 128), np.float32)
    for d in range(128):
        if d % 32 < 16:
            Pm[d, d + 16] = -1.0
        else:
            Pm[d, d - 16] = 1.0
    c["c_perm"] = np.ascontiguousarray(Pm.T)
    sel = np.zeros((128, 2, 128), np.float32)
    sel[64, 0, :] = 1.0
    sel[63, 1, :] = 1.0
    c["c_sel"] = sel
    rm = np.ones((128, 256), np.float32)
    rm[:, ::64] = 0
    c["c_rmask"] = rm
    t = np.arange(T)
    row = (t // 64).astype(np.float32)
    col = (t % 64).astype(np.float32)
    freqs = (np.float32(10000.0) ** (-np.arange(0, 32, 2, dtype=np.float32) / np.float32(32))).astype(np.float32)
    ang = np.zeros((64, T), np.float32)
    for d in range(64):
        pos = row if d < 32 else col
        ang[d] = pos * freqs[d % 16]
    c["cosT"] = np.ascontiguousarray(np.concatenate([np.cos(ang), np.cos(ang)], 0).astype(np.float32))
    c["sinT"] = np.ascontiguousarray(np.concatenate([np.sin(ang), np.sin(ang)], 0).astype(np.float32))
    return c


def host_inputs(inp, b):
    f = lambda k: np.asarray(inp[k], np.float32)
    d = {}
    d["xin"] = np.ascontiguousarray(np.concatenate([f("x")[b].T, f("ctx")[b].T], axis=1))
    d["cvec"] = np.ascontiguousarray(np.stack([fmv(f("c")[b]), fmv(f("c_ctx"))], axis=-1))
    return d


def host_shared(inp):
    f = lambda k: np.asarray(inp[k], np.float32)
    s = dict(host_consts())
    s["w_mod"] = f("w_mod")
    s["b_mod"] = f("b_mod")
    vl = [f("norm_mix")[0], f("norm_mix")[1], f("norm_mlp")[0], f("norm_mlp")[1], f("final_norm")]
    vl += [f("rwkv_mu")[0, j] for j in range(6)]
    vl += [f("rwkv_w0")[0, 0], f("rwkv_w0")[0, 1], f("rwkv_a0")[0, 0], f("rwkv_a0")[0, 1]]
    vl += [f("rwkv_k_k")[0], f("rwkv_k_a")[0], np.zeros(D, np.float32), f("rwkv_r_k")[0].reshape(-1)]
    vl += [np.tile(f("attn_q_norm")[0], 16), np.tile(f("attn_k_norm")[0], 16)]
    assert len(vl) == NV
    s["vecs"] = np.ascontiguousarray(np.stack([fmv(v) for v in vl], axis=1))
    s["mlp_w1"] = f("mlp_w1")
    s["mlp_w2"] = f("mlp_w2")
    for k in ("wr", "wk", "wv", "wo", "w1", "w2", "a1", "a2", "g1", "g2"):
        s["rwkv_" + k] = f("rwkv_" + k)[0]
    lw = f("rwkv_ln_w")[0].reshape(8, 2, 64)
    lb = f("rwkv_ln_b")[0].reshape(8, 2, 64)
    s["lnw_st"] = np.ascontiguousarray(np.repeat(lw.transpose(1, 0, 2), 64, axis=0))
    s["lnb_st"] = np.ascontiguousarray(np.repeat(lb.transpose(1, 0, 2), 64, axis=0))
    wqkv = f("attn_wqkv")[0]
    s["attn_wq"] = np.ascontiguousarray(wqkv[:, :1024])
    wk = wqkv[:, 1024:1280].reshape(D, 4, 64)
    s["attn_wkd"] = np.ascontiguousarray(np.concatenate([wk, wk], axis=2).reshape(D, 512))
    s["attn_wv"] = np.ascontiguousarray(wqkv[:, 1280:1536])
    s["attn_wo"] = f("attn_wo")[0]
    return s


_CACHE = {}


def kernel(**inputs):
    if "prog" not in _CACHE:
        _CACHE["prog"] = build("all")
    nc, used, outnames, _ = _CACHE["prog"]
    shared = host_shared(inputs)
    in_maps = []
    for b in range(NCORES):
        hi = host_inputs(inputs, b)
        hi.update(shared)
        in_maps.append({k: hi[k] for k in used})
    res = run_bass_kernel_spmd(nc, in_maps, core_ids=list(range(NCORES)))
    out = np.stack([np.ascontiguousarray(res.results[b]["y"].T) for b in range(NCORES)], axis=0)
    return out.astype(np.float32)
```
